# Optimizing a Trainium2 kernel written in Bass

```python
import math
import jax, jax.numpy as jnp
from jax import lax
import numpy as np

D_MODEL = 1024
BATCH = 4
SEQ = 4096
DEPTH = 2

MEM_LEN = 256
N_AB = (DEPTH + 1) // 2
N_CD = DEPTH // 2
MIX_W = D_MODEL // 2
NORM_EPS = 1e-6
N_NORMS = 7

GLA_HEADS = 4
GLA_DV = MIX_W // GLA_HEADS
GLA_DK = GLA_DV // 2
GLA_RANK = 16
GLA_TAU = 16.0
GLA_CHUNK = 64

S5_GROUP = 16
S5_GROUPS = MIX_W // S5_GROUP
S5_STATE = 64

RWKV_HEAD = 64
RWKV_HEADS = MIX_W // RWKV_HEAD
RWKV_DECAY_RANK = 64
RWKV_A_RANK = 64
RWKV_GATE_RANK = 128
RWKV_GN_EPS = 64e-5

LRU_BLOCKS = 8
LRU_BLOCK = MIX_W // LRU_BLOCKS
LRU_CONV = 4
LRU_C = 8.0

XA_HEADS = 4
XA_HEAD_DIM = D_MODEL // XA_HEADS
D_FF = 4 * D_MODEL

AB_SIZES = (GLA_HEADS * GLA_DK, GLA_HEADS * GLA_DK, MIX_W, MIX_W, GLA_RANK, MIX_W)
AB_COLS = sum(AB_SIZES)
RWKV_SIZES = (MIX_W, RWKV_DECAY_RANK, MIX_W, MIX_W, RWKV_A_RANK, RWKV_GATE_RANK)
RWKV_COLS = sum(RWKV_SIZES)
CD_SIZES = (RWKV_COLS, MIX_W, MIX_W)
CD_COLS = sum(CD_SIZES)

kernel_name = 'hybrid_gla_s5_rwkv7_rglru_trunk'


def _split(p, sizes):
    return jnp.split(p, [int(s) for s in np.cumsum(sizes)[:-1]], axis=-1)


def rmsnorm(x, gain):
    x32 = x.astype(jnp.float32)
    y = x32 * lax.rsqrt(jnp.mean(x32 * x32, axis=-1, keepdims=True) + NORM_EPS) * gain.astype(jnp.float32)
    return y.astype(x.dtype)


def _linear_scan(a, b, axis):
    def combine(e1, e2):
        a1, b1 = e1
        a2, b2 = e2
        return a1 * a2, a2 * b1 + b2
    _, h = lax.associative_scan(combine, (a, b), axis=axis)
    return h


def gla_mix(q, k, v, g, dlr, w_decay2, b_decay, norm_gain):
    f32 = jnp.float32
    bsz, seq, _ = q.shape
    n_c = seq // GLA_CHUNK
    q = q.astype(f32).reshape(bsz, n_c, GLA_CHUNK, GLA_HEADS, GLA_DK) * GLA_DK ** -0.5
    k = k.astype(f32).reshape(bsz, n_c, GLA_CHUNK, GLA_HEADS, GLA_DK)
    v = v.astype(f32).reshape(bsz, n_c, GLA_CHUNK, GLA_HEADS, GLA_DV)
    log_a = jax.nn.log_sigmoid(dlr.astype(f32) @ w_decay2.astype(f32) + b_decay.astype(f32)) / GLA_TAU
    log_a = log_a.reshape(bsz, n_c, GLA_CHUNK, GLA_HEADS, GLA_DK)
    b = jnp.cumsum(log_a, axis=2)
    b_last = b[:, :, -1:]
    q_in = q * jnp.exp(b)
    k_in = k * jnp.exp(-b)
    scores = jnp.einsum('bnthd,bnshd->bnhts', q_in, k_in)
    causal = jnp.tril(jnp.ones((GLA_CHUNK, GLA_CHUNK), dtype=bool))
    scores = jnp.where(causal, scores, 0.0)
    o_intra = jnp.einsum('bnhts,bnshv->bnthv', scores, v)
    k_state = k * jnp.exp(b_last - b)
    d_state = jnp.einsum('bnshd,bnshv->nbhdv', k_state, v)
    chunk_decay = jnp.transpose(jnp.exp(b_last[:, :, 0]), (1, 0, 2, 3))

    def step(state, inp):
        dec, ds = inp
        return dec[..., None] * state + ds, state

    s0 = jnp.zeros((bsz, GLA_HEADS, GLA_DK, GLA_DV), f32)
    _, s_prev = lax.scan(step, s0, (chunk_decay, d_state))
    o_inter = jnp.einsum('bnthd,nbhdv->bnthv', q_in, s_prev)
    o = (o_intra + o_inter).reshape(bsz, seq, GLA_HEADS, GLA_DV)
    o = rmsnorm(o, norm_gain).reshape(bsz, seq, MIX_W)
    return o * jax.nn.silu(g.astype(f32))


def s5_mix(u, lam_re, lam_im, log_step, b_re, b_im, c_re, c_im, d_skip, w_glu, b_glu):
    f32 = jnp.float32
    bsz, seq, _ = u.shape
    u32 = u.astype(f32)
    ug = u32.reshape(bsz, seq, S5_GROUPS, S5_GROUP)
    lam = lax.complex(jnp.minimum(lam_re.astype(f32), -1e-4), lam_im.astype(f32))
    delta = jnp.exp(log_step.astype(f32))[:, None]
    lam_bar = jnp.exp(lam * delta)
    b_mat = lax.complex(b_re.astype(f32), b_im.astype(f32))
    b_bar = ((lam_bar - 1.0) / lam)[..., None] * b_mat
    bu = jnp.einsum('blgc,gnc->blgn', ug.astype(jnp.complex64), b_bar)
    h = _linear_scan(jnp.broadcast_to(lam_bar, bu.shape), bu, axis=1)
    c_mat = lax.complex(c_re.astype(f32), c_im.astype(f32))
    y = jnp.real(jnp.einsum('blgn,gcn->blgc', h, c_mat)).reshape(bsz, seq, MIX_W)
    y = y + d_skip.astype(f32) * u32
    return jax.nn.gelu(y) * jax.nn.sigmoid(y @ w_glu.astype(f32) + b_glu.astype(f32))


def rwkv7_mix(p, mu, w0, w2, a0, a2, g2, k_k, k_a, r_k, ln_gain, ln_bias):
    f32 = jnp.float32
    bsz, seq, _ = p.shape
    p = p.astype(f32)
    prev = jnp.pad(p, ((0, 0), (1, 0), (0, 0)))[:, :-1]
    p = p + (prev - p) * mu.astype(f32)
    r, w1, k, v, a1, g1 = _split(p, RWKV_SIZES)
    w = -jax.nn.softplus(-(w0.astype(f32) + jnp.tanh(w1) @ w2.astype(f32))) - 0.5
    decay = jnp.exp(-jnp.exp(w))
    a = jax.nn.sigmoid(a0.astype(f32) + a1 @ a2.astype(f32))
    g = jax.nn.sigmoid(g1) @ g2.astype(f32)
    heads = lambda t: t.reshape(bsz, seq, RWKV_HEADS, RWKV_HEAD)
    kk = heads(k * k_k.astype(f32))
    kk = kk / jnp.maximum(jnp.sqrt(jnp.sum(kk * kk, axis=-1, keepdims=True)), 1e-12)
    k = k * (1.0 + (a - 1.0) * k_a.astype(f32))
    r_h, k_h, v_h, a_h, w_h = heads(r), heads(k), heads(v), heads(a), heads(decay)

    def step(state, inp):
        r_t, w_t, k_t, v_t, kk_t, a_t = inp
        sa = jnp.einsum('bhij,bhj->bhi', state, kk_t)
        state = (state * w_t[:, :, None, :]
                 - sa[..., None] * (kk_t * a_t)[:, :, None, :]
                 + v_t[..., None] * k_t[:, :, None, :])
        return state, jnp.einsum('bhij,bhj->bhi', state, r_t)

    tm = lambda t: jnp.moveaxis(t, 1, 0)
    s0 = jnp.zeros((bsz, RWKV_HEADS, RWKV_HEAD, RWKV_HEAD), f32)
    _, y = lax.scan(step, s0, (tm(r_h), tm(w_h), tm(k_h), tm(v_h), tm(kk), tm(a_h)))
    y = jnp.moveaxis(y, 0, 1)
    mean = jnp.mean(y, axis=-1, keepdims=True)
    var = jnp.mean(jnp.square(y - mean), axis=-1, keepdims=True)
    y = ((y - mean) * lax.rsqrt(var + RWKV_GN_EPS)).reshape(bsz, seq, MIX_W)
    y = y * ln_gain.astype(f32) + ln_bias.astype(f32)
    bonus = jnp.sum(r_h * k_h * r_k.astype(f32), axis=-1, keepdims=True) * v_h
    y = y + bonus.reshape(bsz, seq, MIX_W)
    return y * g


def rglru_mix(xb, gate, conv_w, conv_b, w_a, b_a, w_x, b_x, lam):
    f32 = jnp.float32
    bsz, seq, _ = xb.shape
    xc = lax.conv_general_dilated(
        xb.astype(f32), conv_w.astype(f32)[:, None, :], window_strides=(1,),
        padding=((LRU_CONV - 1, 0),), dimension_numbers=('NWC', 'WIO', 'NWC'),
        feature_group_count=MIX_W) + conv_b.astype(f32)
    xg = xc.reshape(bsz, seq, LRU_BLOCKS, LRU_BLOCK)
    r = jax.nn.sigmoid(jnp.einsum('blhi,hij->blhj', xg, w_a.astype(f32)).reshape(bsz, seq, MIX_W) + b_a.astype(f32))
    i = jax.nn.sigmoid(jnp.einsum('blhi,hij->blhj', xg, w_x.astype(f32)).reshape(bsz, seq, MIX_W) + b_x.astype(f32))
    log_a = -LRU_C * r * jax.nn.softplus(-lam.astype(f32))
    a = jnp.exp(log_a)
    mult = jnp.sqrt(-jnp.expm1(2.0 * log_a))
    h = _linear_scan(a, mult * (i * xc), axis=1)
    return h * jax.nn.gelu(gate.astype(f32))


def cross_attention(xn, memn, wq, wk, wv, wo):
    f32 = jnp.float32
    bsz, seq, _ = xn.shape
    q = (xn @ wq).astype(f32).reshape(bsz, seq, XA_HEADS, XA_HEAD_DIM)
    k = (memn @ wk).astype(f32).reshape(bsz, MEM_LEN, XA_HEADS, XA_HEAD_DIM)
    v = (memn @ wv).astype(f32).reshape(bsz, MEM_LEN, XA_HEADS, XA_HEAD_DIM)
    s = jnp.einsum('blhd,bmhd->bhlm', q, k) * XA_HEAD_DIM ** -0.5
    prob = jax.nn.softmax(s, axis=-1)
    o = jnp.einsum('bhlm,bmhd->blhd', prob, v).reshape(bsz, seq, D_MODEL)
    return (o @ wo.astype(f32)).astype(xn.dtype)


def squared_relu_mlp(xn, w1, w2):
    return jnp.square(jax.nn.relu(xn @ w1)) @ w2


def setup_inputs(seed: int = 0) -> dict:
    key = jax.random.key(seed)
    keys = jax.random.split(key, 64)
    counter = [0]

    def nk():
        kk = keys[counter[0]]
        counter[0] += 1
        return kk

    def nrm(shape, scale=1.0):
        return scale * jax.random.normal(nk(), shape, jnp.float32)

    def unif(shape, lo, hi):
        return jax.random.uniform(nk(), shape, jnp.float32, lo, hi)

    n_idx = jnp.arange(S5_STATE, dtype=jnp.float32)
    w0_base = jnp.tile(jnp.linspace(-6.0, -1.0, RWKV_HEAD, dtype=jnp.float32), RWKV_HEADS)
    x = nrm((BATCH, SEQ, D_MODEL))
    mem = nrm((BATCH, MEM_LEN, D_MODEL))
    norm_gain = 1.0 + nrm((DEPTH, N_NORMS, D_MODEL), 0.05)
    xa_wq = nrm((DEPTH, D_MODEL, D_MODEL), D_MODEL ** -0.5)
    xa_wk = nrm((DEPTH, D_MODEL, D_MODEL), D_MODEL ** -0.5)
    xa_wv = nrm((DEPTH, D_MODEL, D_MODEL), D_MODEL ** -0.5)
    xa_wo = nrm((DEPTH, D_MODEL, D_MODEL), D_MODEL ** -0.5)
    mlp_w1 = nrm((DEPTH, D_MODEL, D_FF), D_MODEL ** -0.5)
    mlp_w2 = nrm((DEPTH, D_FF, D_MODEL), D_FF ** -0.5)
    ab_w_in = nrm((N_AB, D_MODEL, AB_COLS), D_MODEL ** -0.5)
    gla_w_decay2 = nrm((N_AB, GLA_RANK, GLA_HEADS * GLA_DK), GLA_RANK ** -0.5)
    gla_b_decay = nrm((N_AB, GLA_HEADS * GLA_DK), 0.1)
    gla_norm_gain = 1.0 + nrm((N_AB, GLA_HEADS, GLA_DV), 0.05)
    s5_lambda_re = -0.5 + nrm((N_AB, S5_GROUPS, S5_STATE), 0.01)
    s5_lambda_im = math.pi * n_idx + nrm((N_AB, S5_GROUPS, S5_STATE), 0.01)
    s5_log_step = unif((N_AB, S5_GROUPS), math.log(1e-3), math.log(1e-1))
    s5_b_re = nrm((N_AB, S5_GROUPS, S5_STATE, S5_GROUP), (2.0 * S5_GROUP) ** -0.5)
    s5_b_im = nrm((N_AB, S5_GROUPS, S5_STATE, S5_GROUP), (2.0 * S5_GROUP) ** -0.5)
    s5_c_re = nrm((N_AB, S5_GROUPS, S5_GROUP, S5_STATE), (2.0 * S5_STATE) ** -0.5)
    s5_c_im = nrm((N_AB, S5_GROUPS, S5_GROUP, S5_STATE), (2.0 * S5_STATE) ** -0.5)
    s5_d = nrm((N_AB, MIX_W))
    s5_w_glu = nrm((N_AB, MIX_W, MIX_W), MIX_W ** -0.5)
    s5_b_glu = nrm((N_AB, MIX_W), 0.01)
    ab_w_out = nrm((N_AB, 2 * MIX_W, D_MODEL), (2 * MIX_W) ** -0.5)
    cd_w_in = nrm((N_CD, D_MODEL, CD_COLS), D_MODEL ** -0.5)
    rwkv_mu = unif((N_CD, RWKV_COLS), 0.0, 1.0)
    rwkv_w0 = w0_base + nrm((N_CD, MIX_W), 0.1)
    rwkv_w2 = nrm((N_CD, RWKV_DECAY_RANK, MIX_W), 0.5 * RWKV_DECAY_RANK ** -0.5)
    rwkv_a0 = nrm((N_CD, MIX_W), 0.1)
    rwkv_a2 = nrm((N_CD, RWKV_A_RANK, MIX_W), 0.5 * RWKV_A_RANK ** -0.5)
    rwkv_g2 = nrm((N_CD, RWKV_GATE_RANK, MIX_W), RWKV_GATE_RANK ** -0.5)
    rwkv_k_k = 0.85 + nrm((N_CD, MIX_W), 0.05)
    rwkv_k_a = 1.0 + nrm((N_CD, MIX_W), 0.05)
    rwkv_r_k = nrm((N_CD, RWKV_HEADS, RWKV_HEAD), 0.1)
    rwkv_ln_gain = 1.0 + nrm((N_CD, MIX_W), 0.05)
    rwkv_ln_bias = nrm((N_CD, MIX_W), 0.01)
    lru_conv_w = nrm((N_CD, LRU_CONV, MIX_W), LRU_CONV ** -0.5)
    lru_conv_b = nrm((N_CD, MIX_W), 0.01)
    lru_w_a = nrm((N_CD, LRU_BLOCKS, LRU_BLOCK, LRU_BLOCK), LRU_BLOCK ** -0.5)
    lru_b_a = nrm((N_CD, MIX_W), 0.01)
    lru_w_x = nrm((N_CD, LRU_BLOCKS, LRU_BLOCK, LRU_BLOCK), LRU_BLOCK ** -0.5)
    lru_b_x = nrm((N_CD, MIX_W), 0.01)
    lru_a = unif((N_CD, MIX_W), 0.9, 0.999) ** (1.0 / LRU_C)
    lru_lambda = jnp.log(lru_a) - jnp.log1p(-lru_a)
    cd_w_out = nrm((N_CD, 2 * MIX_W, D_MODEL), (2 * MIX_W) ** -0.5)
    return {
        'x': x, 'mem': mem, 'norm_gain': norm_gain,
        'xa_wq': xa_wq, 'xa_wk': xa_wk, 'xa_wv': xa_wv, 'xa_wo': xa_wo,
        'mlp_w1': mlp_w1, 'mlp_w2': mlp_w2,
        'ab_w_in': ab_w_in, 'gla_w_decay2': gla_w_decay2, 'gla_b_decay': gla_b_decay,
        'gla_norm_gain': gla_norm_gain,
        's5_lambda_re': s5_lambda_re, 's5_lambda_im': s5_lambda_im, 's5_log_step': s5_log_step,
        's5_b_re': s5_b_re, 's5_b_im': s5_b_im, 's5_c_re': s5_c_re, 's5_c_im': s5_c_im,
        's5_d': s5_d, 's5_w_glu': s5_w_glu, 's5_b_glu': s5_b_glu, 'ab_w_out': ab_w_out,
        'cd_w_in': cd_w_in, 'rwkv_mu': rwkv_mu, 'rwkv_w0': rwkv_w0, 'rwkv_w2': rwkv_w2,
        'rwkv_a0': rwkv_a0, 'rwkv_a2': rwkv_a2, 'rwkv_g2': rwkv_g2, 'rwkv_k_k': rwkv_k_k,
        'rwkv_k_a': rwkv_k_a, 'rwkv_r_k': rwkv_r_k, 'rwkv_ln_gain': rwkv_ln_gain,
        'rwkv_ln_bias': rwkv_ln_bias, 'lru_conv_w': lru_conv_w, 'lru_conv_b': lru_conv_b,
        'lru_w_a': lru_w_a, 'lru_b_a': lru_b_a, 'lru_w_x': lru_w_x, 'lru_b_x': lru_b_x,
        'lru_lambda': lru_lambda, 'cd_w_out': cd_w_out,
    }


def reference(x, mem, norm_gain, xa_wq, xa_wk, xa_wv, xa_wo, mlp_w1, mlp_w2,
              ab_w_in, gla_w_decay2, gla_b_decay, gla_norm_gain,
              s5_lambda_re, s5_lambda_im, s5_log_step, s5_b_re, s5_b_im, s5_c_re, s5_c_im,
              s5_d, s5_w_glu, s5_b_glu, ab_w_out,
              cd_w_in, rwkv_mu, rwkv_w0, rwkv_w2, rwkv_a0, rwkv_a2, rwkv_g2, rwkv_k_k,
              rwkv_k_a, rwkv_r_k, rwkv_ln_gain, rwkv_ln_bias,
              lru_conv_w, lru_conv_b, lru_w_a, lru_b_a, lru_w_x, lru_b_x, lru_lambda, cd_w_out):
    f32 = jnp.float32
    h = x
    for layer in range(DEPTH):
        g = norm_gain[layer]
        i = layer // 2
        hn = rmsnorm(h, g[0])
        if layer % 2 == 0:
            q, k, v, gate, dlr, u = _split(hn @ ab_w_in[i], AB_SIZES)
            o_a = gla_mix(q, k, v, gate, dlr, gla_w_decay2[i], gla_b_decay[i], gla_norm_gain[i])
            o_b = s5_mix(u, s5_lambda_re[i], s5_lambda_im[i], s5_log_step[i], s5_b_re[i], s5_b_im[i],
                         s5_c_re[i], s5_c_im[i], s5_d[i], s5_w_glu[i], s5_b_glu[i])
            mix = jnp.concatenate([o_a, o_b], axis=-1) @ ab_w_out[i].astype(f32)
        else:
            p_rwkv, xb, gate = _split(hn @ cd_w_in[i], CD_SIZES)
            o_c = rwkv7_mix(p_rwkv, rwkv_mu[i], rwkv_w0[i], rwkv_w2[i], rwkv_a0[i], rwkv_a2[i],
                            rwkv_g2[i], rwkv_k_k[i], rwkv_k_a[i], rwkv_r_k[i],
                            rwkv_ln_gain[i], rwkv_ln_bias[i])
            o_d = rglru_mix(xb, gate, lru_conv_w[i], lru_conv_b[i], lru_w_a[i], lru_b_a[i],
                            lru_w_x[i], lru_b_x[i], lru_lambda[i])
            mix = jnp.concatenate([o_c, o_d], axis=-1) @ cd_w_out[i].astype(f32)
        h = h + rmsnorm(mix, g[1]).astype(h.dtype)
        memn = rmsnorm(mem, g[6])
        xa = cross_attention(rmsnorm(h, g[2]), memn, xa_wq[layer], xa_wk[layer], xa_wv[layer], xa_wo[layer])
        h = h + rmsnorm(xa, g[3]).astype(h.dtype)
        ff = squared_relu_mlp(rmsnorm(h, g[4]), mlp_w1[layer], mlp_w2[layer])
        h = h + rmsnorm(ff, g[5]).astype(h.dtype)
    return h
```

```python
import os
import math
from contextlib import ExitStack


import numpy as np
import concourse.bass as bass
import concourse.mybir as mybir
from concourse.bass_utils import run_bass_kernel_spmd

F32 = mybir.dt.float32
BF16 = mybir.dt.bfloat16
I32 = mybir.dt.int32
AF = mybir.ActivationFunctionType
ALU = mybir.AluOpType
AX = mybir.AxisListType

ENGS = ['pe', 'act', 'dve', 'pool', 'sp']
NDMA_SLOTS = 8
SAME_ENGINE_SYNC = os.environ.get("NOSELF", "0") != "1"


class Prog:
    def __init__(self, nc):
        self.nc = nc
        self.ops = {e: [] for e in ENGS}
        self.cnt = {e: 0 for e in ENGS}
        self.last_w = {}
        self.readers = {}
        self.seen = {e: {} for e in ENGS}
        self.dma_n = {e: 0 for e in ENGS}
        self.dma_tok = {e: [None] * NDMA_SLOTS for e in ENGS}
        self.final_tokens = []
        from contextlib import ExitStack
        self.sem_stack = ExitStack()
        self.sems = {}
        for e in ['pe', 'act', 'dve', 'pool']:
            self.sems[('c', e)] = self.sem_stack.enter_context(nc.semaphore("s_c_" + e))
        for q in ['sp', 'pool']:
            for sl in range(NDMA_SLOTS):
                self.sems[('d', q, sl)] = self.sem_stack.enter_context(nc.semaphore(f"s_d_{q}_{sl}"))

    def barrier(self):
        toks = []
        for e in ['pe', 'act', 'dve', 'pool']:
            if self.cnt[e] > 0:
                toks.append((('c', e), self.cnt[e]))
        for q in ENGS:
            for t in self.dma_tok[q]:
                if t is not None:
                    toks.append(t)
        for e in ENGS:
            waits = []
            for (sem, val) in toks:
                if sem == ('c', e):
                    continue
                if self.seen[e].get(sem, 0) >= val:
                    continue
                waits.append((sem, val))
                self.seen[e][sem] = val
            if waits:
                self.ops[e].append((waits, None, None))
        self.last_w = {}
        self.readers = {}

    def _deps(self, eng, reads, writes):
        toks = []
        for r in reads:
            t = self.last_w.get(r)
            if t is not None:
                toks.append(t)
        for w in writes:
            t = self.last_w.get(w)
            if t is not None:
                toks.append(t)
            toks.extend(self.readers.get(w, []))
        need = {}
        for (sem, val) in toks:
            if not SAME_ENGINE_SYNC and sem == ('c', eng):
                continue
            if sem == ('c', 'pe') and eng == 'pe':
                continue
            if self.seen[eng].get(sem, 0) >= val:
                continue
            if need.get(sem, 0) < val:
                need[sem] = val
        for sem, val in need.items():
            self.seen[eng][sem] = val
        return list(need.items())

    def _commit(self, tok, reads, writes):
        for w in writes:
            self.last_w[w] = tok
            self.readers[w] = []
        for r in reads:
            if r in writes:
                continue
            self.readers.setdefault(r, []).append(tok)

    def op(self, eng, fn, reads=(), writes=()):
        self.nrec = getattr(self, 'nrec', 0) + 1
        if self.nrec > int(os.environ.get("MAXOPS", "100000000")):
            return None
        kp = getattr(self, 'key_prefix', '')
        reads = [r if r.startswith('ps') else kp + r for r in reads]
        writes = [w if w.startswith('ps') else kp + w for w in writes]
        pk = getattr(self, 'ps_prefix', '')
        reads = [('ps' + pk + r[2:]) if r.startswith('ps') else r for r in reads]
        writes = [('ps' + pk + w[2:]) if w.startswith('ps') else w for w in writes]
        writes = list(writes) + [r for r in reads if r.startswith('ps') and r not in writes]
        waits = self._deps(eng, reads, writes)
        self.cnt[eng] += 1
        tok = (('c', eng), self.cnt[eng])
        self.ops[eng].append((waits, fn, tok))
        self._commit(tok, reads, writes)
        return tok

    def dma(self, q, out, in_, reads=(), writes=(), final=False, **kw):
        self.nrec = getattr(self, 'nrec', 0) + 1
        if self.nrec > int(os.environ.get("MAXOPS", "100000000")):
            return None
        kp = getattr(self, 'key_prefix', '')
        reads = [kp + r for r in reads]
        writes = [kp + w for w in writes]
        waits = self._deps(q, reads, writes)
        n = self.dma_n[q]
        slot = n % NDMA_SLOTS
        prev = self.dma_tok[q][slot]
        if prev is not None and self.seen[q].get(prev[0], 0) < prev[1]:
            waits.append(prev)
            self.seen[q][prev[0]] = prev[1]
        tok = (('d', q, slot), 16 * (n // NDMA_SLOTS + 1))
        self.dma_n[q] += 1
        self.dma_tok[q][slot] = tok

        def fn(e, out=out, in_=in_, kw=kw):
            return e.dma_start(out=out, in_=in_, **kw)
        self.ops[q].append((waits, fn, tok))
        self._commit(tok, reads, writes)
        if final:
            self.final_tokens.append(tok)
        return tok

    def emit(self, last=True):
        nc = self.nc
        sems = self.sems
        with nc.Block() as block:
            final = list(self.final_tokens) if last else []

            def run(e, name):
                for waits, fn, tok in self.ops[name]:
                    for (s, v) in waits:
                        e.wait_ge(sems[s], v)
                    if fn is None:
                        continue
                    inst = fn(e)
                    inc = 16 if tok[0][0] == 'd' else 1
                    inst.then_inc(sems[tok[0]], inc)
                if name == 'sp':
                    for (s, v) in final:
                        e.wait_ge(sems[s], v)
                self.ops[name] = []

            @block.tensor
            def _(e):
                run(e, 'pe')

            @block.scalar
            def _(e):
                run(e, 'act')

            @block.vector
            def _(e):
                run(e, 'dve')

            @block.gpsimd
            def _(e):
                run(e, 'pool')

            @block.sync
            def _(e):
                run(e, 'sp')
        if last:
            self.sem_stack.close()


D = 1024
KC = 8
EPS = 1e-6


class K:
    def __init__(self, fused=False):
        self.nc = bass.Bass("TRN2", target_bir_lowering=False)
        self.st = ExitStack()
        self.P = Prog(self.nc)
        self.n = 0
        self.fused = fused
        self.io = {}
        self.pfx = ""

    def begin_phase(self, name, io):
        self.pfx = name + "_"
        self.io = io
        self.st = ExitStack()
        for a in ('wstage', 'rr_cache', 'identf', 'identb'):
            if hasattr(self, a):
                delattr(self, a)

    def scratch(self, name, shape, dt=F32):
        return self.nc.dram_tensor(name, list(shape), dt, kind="Internal").ap()

    def xin(self, name, arr_shape, dt=F32):
        return self.nc.dram_tensor(name, list(arr_shape), dt, kind="ExternalInput").ap()

    def xout(self, name, arr_shape, dt=F32):
        return self.nc.dram_tensor(name, list(arr_shape), dt, kind="ExternalOutput").ap()

    def din(self, name, shape, dt=F32):
        if self.fused:
            ap = self.io[name]
            assert list(ap.shape) == list(shape), (name, ap.shape, shape)
            return ap
        return self.nc.dram_tensor(name, list(shape), dt, kind="ExternalInput").ap()

    def dout(self, name, shape, dt=F32):
        if self.fused:
            ap = self.io[name]
            assert list(ap.shape) == list(shape), (name, ap.shape, shape)
            return ap
        return self.nc.dram_tensor(name, list(shape), dt, kind="ExternalOutput").ap()

    def sb(self, name, shape, dt=F32):
        return self.st.enter_context(self.nc.sbuf_tensor(self.pfx + name, list(shape), dt))

    def ps(self, name, shape, dt=F32):
        return self.st.enter_context(self.nc.psum_tensor(self.pfx + name, list(shape), dt))

    def finish(self, last=True):
        if self.fused:
            self.P.barrier()
            self.P.emit(last=False)
            self.st.close()
            return None
        self.P.emit()
        self.st.close()
        return self.nc

    def finish_program(self):
        self.P.emit(last=True)
        return self.nc

    def mm(self, out, lhsT, rhs, start, stop, r, w):
        self.P.op('pe', lambda e: e.matmul(out, lhsT=lhsT, rhs=rhs, start=start, stop=stop), reads=r, writes=w)

    def tr(self, out, in_, ident, r, w):
        self.P.op('pe', lambda e: e.transpose(out=out, in_=in_, identity=ident), reads=list(r) + ['ident'], writes=w)

    def act(self, out, in_, func, r, w, **kw):
        self.P.op('act', lambda e: e.activation(out=out, in_=in_, func=func, **kw), reads=r, writes=w)

    def tt(self, eng, out, in0, in1, op, r, w):
        self.P.op(eng, lambda e: e.tensor_tensor(out=out, in0=in0, in1=in1, op=op), reads=r, writes=w)

    def ts(self, eng, out, in0, s1, s2, op0, op1, r, w):
        if op1 is None:
            self.P.op(eng, lambda e: e.tensor_scalar(out=out, in0=in0, scalar1=s1, scalar2=None, op0=op0), reads=r, writes=w)
        else:
            self.P.op(eng, lambda e: e.tensor_scalar(out=out, in0=in0, scalar1=s1, scalar2=s2, op0=op0, op1=op1), reads=r, writes=w)

    def stt(self, out, in0, scalar, in1, op0, op1, r, w):
        self.P.op('dve', lambda e: e.scalar_tensor_tensor(out=out, in0=in0, scalar=scalar, in1=in1, op0=op0, op1=op1),
                  reads=r, writes=w)

    def cp(self, eng, out, in_, r, w):
        if eng == 'act':
            self.P.op('act', lambda e: e.copy(out=out, in_=in_), reads=r, writes=w)
        else:
            self.P.op(eng, lambda e: e.tensor_copy(out=out, in_=in_), reads=r, writes=w)

    def recip(self, out, in_, r, w):
        self.P.op('dve', lambda e: e.reciprocal(out=out, in_=in_), reads=r, writes=w)

    def memset(self, eng, ap, val, w):
        self.P.op(eng, lambda e: e.memset(ap, val), reads=[], writes=w)

    def dma(self, q, out, in_, r=(), w=(), final=False, **kw):
        self.P.dma(q, out, in_, reads=r, writes=w, final=final, **kw)

    def consts(self, ident_d):
        self.identf = self.sb("identf", [128, 128], F32)
        self.identb = self.sb("identb", [128, 128], BF16)
        self.dma('sp', self.identf[:], ident_d, w=['ident'])
        self.cp('dve', self.identb[:], self.identf[:], ['ident'], ['ident'])

    def gain_cols(self, name, g_d):
        t = self.sb(name, [128, KC], F32)
        self.dma('sp', t[:], g_d.rearrange("(kc p) -> p kc", p=128), w=[name], allow_slow_non_contiguous=True)
        return t

    def bcast_row(self, name, vec_d, n):
        t = self.sb(name, [128, n], F32)
        self.dma('sp', t[:], vec_d.partition_broadcast(128), w=[name])
        return t

    def load_weight(self, name, w_d, kchunks, ncols, gcol=None, gkey=None, stage_cols=2048, q='pool'):
        wb = self.sb(name, [128, kchunks, ncols], BF16)
        if not hasattr(self, 'wstage'):
            self.wstage = [self.sb(f"wstage{i}", [128, stage_cols], F32) for i in range(2)]
            self.wstage_n = 0
            self.wstage_cols = stage_cols
        sc = self.wstage_cols
        wv = w_d.rearrange("(kc p) n -> p kc n", p=128)
        for kc in range(kchunks):
            for c0 in range(0, ncols, sc):
                cw = min(sc, ncols - c0)
                b = self.wstage_n % 2
                self.wstage_n += 1
                stg = self.wstage[b]
                self.dma(q, stg[:, 0:cw], wv[:, kc, c0:c0 + cw], w=[f'wstage{b}'])
                eng = 'act' if (kc % 2 == 0) else 'dve'
                if gcol is not None:
                    if eng == 'act':
                        self.act(wb[:, kc, c0:c0 + cw], stg[:, 0:cw], AF.Copy, [f'wstage{b}', gkey], [f'{name}{kc}'],
                                 scale=gcol[:, kc:kc + 1])
                    else:
                        self.ts('dve', wb[:, kc, c0:c0 + cw], stg[:, 0:cw], gcol[:, kc:kc + 1], None, ALU.mult, None,
                                [f'wstage{b}', gkey], [f'{name}{kc}'])
                else:
                    self.cp(eng, wb[:, kc, c0:c0 + cw], stg[:, 0:cw], [f'wstage{b}'], [f'{name}{kc}'])
        return wb

    def rstd_of(self, x_ap, xkey, ss, rstd, junk, key, ncols=D):
        self.act(junk, x_ap, AF.Square, [xkey], ['junk', key + 'ss'], accum_out=ss)
        self.ts('dve', rstd, ss, 1.0 / ncols, EPS, ALU.mult, ALU.add, [key + 'ss'], [key])
        self.act(rstd, rstd, AF.Sqrt, [key], [key])
        self.recip(rstd, rstd, [key], [key])


def pipeline(make_gen, n):
    active = []
    for i in range(n):
        for g in list(active):
            try:
                next(g)
            except StopIteration:
                active.remove(g)
        g = make_gen(i)
        active.append(g)
        try:
            next(g)
        except StopIteration:
            active.remove(g)
    while active:
        for g in list(active):
            try:
                next(g)
            except StopIteration:
                active.remove(g)


def run_streams(k, streams):
    base_pfx = k.pfx
    gens = []
    for (pf, io, gf) in streams:
        gens.append([pf, io, None, gf])
    active = list(gens)
    while active:
        for st in list(active):
            pf, io, g, gf = st
            k.pfx = base_pfx + pf
            k.P.key_prefix = pf
            k.P.ps_prefix = pf
            k.io = io
            try:
                if g is None:
                    st[2] = gf(k)
                    g = st[2]
                next(g)
            except StopIteration:
                active.remove(st)
    k.pfx = base_pfx
    k.P.key_prefix = ''
    k.P.ps_prefix = ''


GELU_C = 1.5957691216057308


def norm_T(k, xt, xkey, xn, xnkey, xT_dst, xTkey, psT, psTkey, ss, rstd, junk, key, evac_eng='act'):
    k.rstd_of(xt, xkey, ss, rstd, junk, key)
    k.ts('dve', xn, xt, rstd, None, ALU.mult, None, [xkey, key], [xnkey])
    for kc in range(KC):
        k.tr(psT[:, kc * 128:(kc + 1) * 128], xn[:, kc * 128:(kc + 1) * 128], k.identb[:], [xnkey], [psTkey])
    k.cp(evac_eng, xT_dst, psT[:].rearrange("p (k t) -> p k t", k=KC), [psTkey], [xTkey])


def post_norm_res(k, ps2, pskeys, ht, hkey, gbc, gkey, tmp2, tmpkeys, ss2, rstd, junk, key):
    for j in range(2):
        k.act(junk[:, 0:512], ps2[j], AF.Square, [pskeys[j]], ['junk', key + f'ss{j}'], accum_out=ss2[:, j:j + 1])
    k.tt('dve', ss2[:, 0:1], ss2[:, 0:1], ss2[:, 1:2], ALU.add, [key + 'ss0', key + 'ss1'], [key + 'ss0'])
    k.ts('dve', rstd, ss2[:, 0:1], 1.0 / D, EPS, ALU.mult, ALU.add, [key + 'ss0'], [key])
    k.act(rstd, rstd, AF.Sqrt, [key], [key])
    k.recip(rstd, rstd, [key], [key])
    for j in range(2):
        sl = slice(j * 512, (j + 1) * 512)
        k.stt(tmp2[j], ps2[j], rstd, gbc[:, sl], ALU.mult, ALU.mult, [pskeys[j], key, gkey], [tmpkeys[j]])
        k.tt('pool', ht[:, sl], ht[:, sl], tmp2[j], ALU.add, [tmpkeys[j], hkey], [hkey])


def build_C1(NTOK, glu, k=None, ob_fm=False):
    k = k or K()
    NT = NTOK // 128
    NB = 3
    oa = k.din("oa", [NTOK, 512])
    if ob_fm:
        obT = k.din("obT", [512, NTOK])
    else:
        ob = k.din("ob", [NTOK, 512])
    hin = k.din("hin", [NTOK, D])
    wout = k.din("wout", [D, D])
    g1 = k.din("g1", [D])
    ident_d = k.din("ident", [128, 128])
    if glu:
        wglu = k.din("wglu", [512, 512])
        bglu = k.din("bglu", [512])
    hout = k.dout("hout", [NTOK, D])
    k.consts(ident_d)
    g1bc = k.bcast_row("g1bc", g1, D)
    Wout = k.load_weight("Wout", wout, KC, D, stage_cols=1024)
    if glu:
        Wglu = k.load_weight("Wglu", wglu, 4, 512)
        bgbc = k.bcast_row("bgbc", bglu, 512)
    R = range(NB)
    oc = [k.sb(f"oc{i}", [128, D]) for i in R]
    ocb = [k.sb(f"ocb{i}", [128, D], BF16) for i in R]
    oT = [k.sb(f"oT{i}", [128, KC, 128], BF16) for i in R]
    ht = [k.sb(f"ht{i}", [128, D]) for i in R]
    if ob_fm:
        obt = [k.sb(f"obt{i}", [128, 4, 128]) for i in R]
    tmp = [[k.sb(f"tmp{i}_{j}", [128, 512]) for j in range(2)] for i in range(2)]
    junk = k.sb("junk", [128, D], BF16)
    ss2 = [k.sb(f"ss2{i}", [128, 2]) for i in R]
    rstd = [k.sb(f"rstd{i}", [128, 1]) for i in R]
    if glu:
        yb = [k.sb(f"yb{i}", [128, 512], BF16) for i in R]
        yT = [k.sb(f"yT{i}", [128, 4, 128], BF16) for i in R]
        zs = [k.sb(f"zs{i}", [128, 512]) for i in R]
        t1 = [k.sb(f"t1{i}", [128, 512]) for i in R]
        t2 = [k.sb(f"t2{i}", [128, 512]) for i in R]
    psT = [k.ps(f"psT{i}", [128, D], BF16) for i in range(2)]
    psM = [k.ps(f"psM{i}", [128, 512]) for i in range(4)]
    if glu:
        psG = [k.ps(f"psG{i}", [128, 512]) for i in range(2)]

    def tile(i):
        b = i % NB
        b2 = i % 2
        rows = slice(i * 128, (i + 1) * 128)
        k.dma('sp', oc[b][:, 0:512], oa[rows, :], w=[f'oA{b}'])
        if ob_fm:
            k.dma('sp', obt[b][:], obT[:, rows].rearrange("(a p) t -> p a t", p=128), w=[f'obt{b}'])
        else:
            k.dma('sp', oc[b][:, 512:1024], ob[rows, :], w=[f'oB{b}'])
        k.dma('sp', ht[b][:], hin[rows, :], w=[f'ht{b}'])
        if glu:
            y = oc[b][:, 512:1024]
            k.cp('dve', yb[b][:], y, [f'oB{b}'], [f'yb{b}'])
            for kc in range(4):
                k.tr(psT[b2][:, kc * 128:(kc + 1) * 128], yb[b][:, kc * 128:(kc + 1) * 128], k.identb[:], [f'yb{b}'], [f'psT{b2}'])
            k.cp('act', yT[b][:], psT[b2][:, 0:512].rearrange("p (k t) -> p k t", k=4), [f'psT{b2}'], [f'yT{b}'])
            for kc in range(4):
                k.mm(psG[b2][:], yT[b][:, kc, :], Wglu[:, kc, :], kc == 0, kc == 3, [f'yT{b}', f'Wglu{kc}'], [f'psG{b2}'])
            k.act(t1[b][:], y, AF.Square, [f'oB{b}'], [f't1{b}'])
            k.act(t1[b][:], t1[b][:], AF.Copy, [f't1{b}'], [f't1{b}'], scale=0.044715, bias=1.0)
            k.tt('pool', t1[b][:], t1[b][:], y, ALU.mult, [f't1{b}', f'oB{b}'], [f't1{b}'])
            k.act(t1[b][:], t1[b][:], AF.Sigmoid, [f't1{b}'], [f't1{b}'], scale=GELU_C)
            yield
            k.tt('dve', zs[b][:], psG[b2][:], bgbc[:], ALU.add, [f'psG{b2}', 'bgbc'], [f'zs{b}'])
            k.act(zs[b][:], zs[b][:], AF.Sigmoid, [f'zs{b}'], [f'zs{b}'])
            k.tt('dve', t2[b][:], t1[b][:], zs[b][:], ALU.mult, [f't1{b}', f'zs{b}'], [f't2{b}'])
            k.tt('dve', y, y, t2[b][:], ALU.mult, [f'oB{b}', f't2{b}'], [f'oB{b}'])
        if ob_fm:
            k.cp('dve', ocb[b][:, 0:512], oc[b][:, 0:512], [f'oA{b}'], [f'ocb{b}'])
            for kc in range(4):
                k.tr(psT[b2][:, kc * 128:(kc + 1) * 128], ocb[b][:, kc * 128:(kc + 1) * 128], k.identb[:], [f'ocb{b}'], [f'psT{b2}'])
            k.cp('act', oT[b][:, 0:4, :], psT[b2][:, 0:512].rearrange("p (k t) -> p k t", k=4), [f'psT{b2}'], [f'oT{b}'])
            k.cp('pool', oT[b][:, 4:8, :], obt[b][:], [f'obt{b}'], [f'oTb{b}'])
        else:
            k.cp('dve', ocb[b][:], oc[b][:], [f'oA{b}', f'oB{b}'], [f'ocb{b}'])
            for kc in range(KC):
                k.tr(psT[b2][:, kc * 128:(kc + 1) * 128], ocb[b][:, kc * 128:(kc + 1) * 128], k.identb[:], [f'ocb{b}'], [f'psT{b2}'])
            k.cp('act', oT[b][:], psT[b2][:].rearrange("p (k t) -> p k t", k=KC), [f'psT{b2}'], [f'oT{b}'])
        yield
        for cg in range(2):
            pm = 2 * b2 + cg
            for kc in range(KC):
                ok_ = f'oTb{b}' if (ob_fm and kc >= 4) else f'oT{b}'
                k.mm(psM[pm][:], oT[b][:, kc, :], Wout[:, kc, cg * 512:(cg + 1) * 512], kc == 0, kc == KC - 1,
                     [ok_, f'Wout{kc}'], [f'psM{pm}'])
        post_norm_res(k, [psM[2 * b2][:], psM[2 * b2 + 1][:]], [f'psM{2 * b2}', f'psM{2 * b2 + 1}'], ht[b], f'ht{b}',
                      g1bc, 'g1bc', [tmp[b2][0][:], tmp[b2][1][:]], [f'tmp{b2}0', f'tmp{b2}1'], ss2[b], rstd[b][:], junk, f'pn{b}')
        k.dma('pool', hout[rows, :], ht[b][:], r=[f'ht{b}'], final=True)

    pipeline(tile, NT)
    return k.finish()


def build_C3(NTOK, k=None):
    k = k or K()
    NB = NTOK // 512
    DFF = 4096
    FC = DFF // 128
    hin = k.din("hin", [NTOK, D])
    w1 = k.din("w1", [D, DFF])
    w2 = k.din("w2", [DFF, D])
    g4 = k.din("g4", [D])
    g5 = k.din("g5", [D])
    ident_d = k.din("ident", [128, 128])
    hout = k.dout("hout", [NTOK, D])
    k.consts(ident_d)
    g4c = k.gain_cols("g4c", g4)
    g5bc = k.bcast_row("g5bc", g5, D)
    W1 = k.load_weight("W1", w1, KC, DFF, gcol=g4c, gkey='g4c', stage_cols=512)
    W2 = k.load_weight("W2", w2, FC, D, stage_cols=512)
    ht = [k.sb(f"ht{i}", [128, D]) for i in range(4)]
    xn = [k.sb(f"xn{i}", [128, D], BF16) for i in range(2)]
    xT = k.sb("xT", [128, KC, 512], BF16)
    AT = k.sb("AT", [128, FC, 512], BF16)
    sq = [k.sb(f"sq{i}", [128, 512]) for i in range(2)]
    junk = k.sb("junk", [128, D], BF16)
    ss = [k.sb(f"ss{i}", [128, 1]) for i in range(2)]
    ss2 = [k.sb(f"ss2{i}", [128, 2]) for i in range(2)]
    rstd = [k.sb(f"rstd{i}", [128, 1]) for i in range(2)]
    rstd2 = [k.sb(f"rstdb{i}", [128, 1]) for i in range(2)]
    psT = k.ps("psT", [128, D], BF16)
    psU = [k.ps(f"psU{i}", [128, 512]) for i in range(3)]
    psD = [k.ps(f"psD{i}", [128, 512]) for i in range(4)]
    nu = 0
    for blk in range(NB):
        for tt in range(4):
            i = blk * 4 + tt
            b = i % 2
            rows = slice(i * 128, (i + 1) * 128)
            k.dma('sp', ht[tt][:], hin[rows, :], w=[f'ht{tt}'])
            norm_T(k, ht[tt][:], f'ht{tt}', xn[b][:], f'xn{b}', xT[:, :, tt * 128:(tt + 1) * 128], 'xT', psT[:], 'psT',
                   ss[b][:], rstd[b][:], junk[:], f'n{b}')
        for fc in range(FC):
            pu = nu % 3
            nu += 1
            for kc in range(KC):
                k.mm(psU[pu][:], W1[:, kc, fc * 128:(fc + 1) * 128], xT[:, kc, :], kc == 0, kc == KC - 1,
                     [f'W1{kc}', 'xT'], [f'psU{pu}'])
            sb_ = fc % 2
            k.act(sq[sb_][:], psU[pu][:], AF.Square, [f'psU{pu}'], [f'sq{sb_}'])
            k.stt(AT[:, fc, :], psU[pu][:], 0.0, sq[sb_][:], ALU.is_gt, ALU.mult, [f'psU{pu}', f'sq{sb_}'], ['AT'])
        for tt in range(4):
            i = blk * 4 + tt
            b = i % 2
            rows = slice(i * 128, (i + 1) * 128)
            for cg in range(2):
                pd = 2 * b + cg
                for fc in range(FC):
                    k.mm(psD[pd][:], AT[:, fc, tt * 128:(tt + 1) * 128], W2[:, fc, cg * 512:(cg + 1) * 512],
                         fc == 0, fc == FC - 1, ['AT', f'W2{fc}'], [f'psD{pd}'])
            post_norm_res(k, [psD[2 * b][:], psD[2 * b + 1][:]], [f'psD{2 * b}', f'psD{2 * b + 1}'], ht[tt], f'ht{tt}',
                          g5bc, 'g5bc', [sq[0][:], sq[1][:]], ['sq0', 'sq1'], ss2[b], rstd2[b][:], junk, f'pn{b}')
            k.dma('pool', hout[rows, :], ht[tt][:], r=[f'ht{tt}'], final=True)
    return k.finish()


def build_C2(NTOK, k=None):
    k = k or K()
    NB = NTOK // 512
    MEM = 256
    hin = k.din("hin", [NTOK, D])
    mem = k.din("mem", [MEM, D])
    wq = k.din("wq", [D, D])
    wk = k.din("wk", [D, D])
    wv = k.din("wv", [D, D])
    wo = k.din("wo", [D, D])
    g2 = k.din("g2", [D])
    g3 = k.din("g3", [D])
    g6 = k.din("g6", [D])
    ident_d = k.din("ident", [128, 128])
    hout = k.dout("hout", [NTOK, D])
    k.consts(ident_d)
    g2c = k.gain_cols("g2c", g2)
    g6c = k.gain_cols("g6c", g6)
    g3bc = k.bcast_row("g3bc", g3, D)
    Wk = k.load_weight("Wk", wk, KC, D, gcol=g6c, gkey='g6c', stage_cols=1024)
    Wv = k.load_weight("Wv", wv, KC, D, gcol=g6c, gkey='g6c', stage_cols=1024)
    Wq = k.load_weight("Wq", wq, KC, D, gcol=g2c, gkey='g2c', stage_cols=1024)
    Wo = k.load_weight("Wo", wo, KC, D, stage_cols=1024)
    ht = [k.sb(f"ht{i}", [128, D]) for i in range(8)]
    xn = [k.sb(f"xn{i}", [128, D], BF16) for i in range(2)]
    xT = [k.sb(f"xT{i}", [128, KC, 512], BF16) for i in range(2)]
    memT = k.sb("memT", [128, KC, MEM], BF16)
    KT = k.sb("KT", [128, KC, MEM], BF16)
    V = k.sb("V", [128, 2, D], BF16)
    QT = [k.sb(f"QT{i}", [128, KC, 512], BF16) for i in range(2)]
    Pm = [k.sb(f"Pm{i}", [128, 4, MEM], BF16) for i in range(3)]
    Pn = [k.sb(f"Pn{i}", [128, 4, MEM], BF16) for i in range(3)]
    PT = [k.sb(f"PT{i}", [128, 8, 128], BF16) for i in range(3)]
    OT = [k.sb(f"OT{i}", [128, KC, 128], BF16) for i in range(3)]
    tmp = [k.sb(f"tmp{i}", [128, 512]) for i in range(2)]
    junk = k.sb("junk", [128, D], BF16)
    ss = [k.sb(f"ss{i}", [128, 1]) for i in range(2)]
    ss2 = [k.sb(f"ss2{i}", [128, 2]) for i in range(2)]
    rstd = [k.sb(f"rstd{i}", [128, 1]) for i in range(2)]
    rstd2 = [k.sb(f"rstdb{i}", [128, 1]) for i in range(2)]
    mx = [k.sb(f"mx{i}", [128, 4]) for i in range(3)]
    sm = [k.sb(f"sm{i}", [128, 4]) for i in range(3)]
    psT = k.ps("psT", [128, D], BF16)
    psA = k.ps("psA", [128, 1024])
    psS = k.ps("psS", [128, 1024])
    psX = k.ps("psX", [128, 1024])
    for mt in range(2):
        k.dma('sp', ht[mt][:], mem[mt * 128:(mt + 1) * 128, :], w=[f'ht{mt}'])
        norm_T(k, ht[mt][:], f'ht{mt}', xn[mt][:], f'xn{mt}', memT[:, :, mt * 128:(mt + 1) * 128], 'memT', psT[:], 'psT',
               ss[mt][:], rstd[mt][:], junk[:], f'n{mt}')
    for cc in range(KC):
        pa = cc % 2
        for kc in range(KC):
            k.mm(psA[:, pa * 512:pa * 512 + MEM], Wk[:, kc, cc * 128:(cc + 1) * 128], memT[:, kc, :], kc == 0, kc == KC - 1,
                 [f'Wk{kc}', 'memT'], [f'psA{pa}'])
        k.cp('act' if cc % 2 else 'dve', KT[:, cc, :], psA[:, pa * 512:pa * 512 + MEM], [f'psA{pa}'], [f'KT{cc}'])
    for mt in range(2):
        for cg in range(2):
            for kc in range(KC):
                k.mm(psX[:, cg * 512:(cg + 1) * 512], memT[:, kc, mt * 128:(mt + 1) * 128], Wv[:, kc, cg * 512:(cg + 1) * 512],
                     kc == 0, kc == KC - 1, ['memT', f'Wv{kc}'], [f'psX{cg}'])
            k.cp('act' if cg else 'dve', V[:, mt, cg * 512:(cg + 1) * 512], psX[:, cg * 512:(cg + 1) * 512], [f'psX{cg}'], [f'V{mt}{cg}'])
    def tile(i):
        blk, tt = divmod(i, 4)
        xb = blk % 2
        b = i % 3
        rows = slice(i * 128, (i + 1) * 128)
        tsl = slice(tt * 128, (tt + 1) * 128)
        hb = xb * 4 + tt
        if tt == 0:
            for t2_ in range(4):
                i2 = blk * 4 + t2_
                b2 = i2 % 2
                hb2 = xb * 4 + t2_
                k.dma('sp', ht[hb2][:], hin[i2 * 128:(i2 + 1) * 128, :], w=[f'ht{hb2}'])
                norm_T(k, ht[hb2][:], f'ht{hb2}', xn[b2][:], f'xn{b2}', xT[xb][:, :, t2_ * 128:(t2_ + 1) * 128], f'xT{xb}', psT[:], 'psT',
                       ss[b2][:], rstd[b2][:], junk[:], f'n{b2}')
            for cc in range(KC):
                pa = cc % 2
                for kc in range(KC):
                    k.mm(psA[:, pa * 512:(pa + 1) * 512], Wq[:, kc, cc * 128:(cc + 1) * 128], xT[xb][:, kc, :], kc == 0, kc == KC - 1,
                         [f'Wq{kc}', f'xT{xb}'], [f'psA{pa}'])
                k.cp('act' if cc % 2 else 'dve', QT[xb][:, cc, :], psA[:, pa * 512:(pa + 1) * 512], [f'psA{pa}'], [f'QT{xb}{cc}'])
        for h in range(4):
            sb_ = h // 2
            for j in range(2):
                cc = 2 * h + j
                k.mm(psS[:, h * MEM:(h + 1) * MEM], QT[xb][:, cc, tsl], KT[:, cc, :], j == 0, j == 1,
                     [f'QT{xb}{cc}', f'KT{cc}'], [f'psS{sb_}'])
        k.P.op('dve', lambda e, b=b: e.tensor_reduce(out=mx[b][:], in_=psS[:].rearrange("p (h m) -> p h m", h=4),
                                                    axis=AX.X, op=ALU.max),
               reads=['psS0', 'psS1'], writes=[f'mx{b}'])
        k.ts('dve', mx[b][:], mx[b][:], -1.0 / 16.0, None, ALU.mult, None, [f'mx{b}'], [f'mx{b}'])
        for h in range(4):
            k.act(Pm[b][:, h, :], psS[:, h * MEM:(h + 1) * MEM], AF.Exp, [f'psS{h // 2}', f'mx{b}'], [f'Pm{b}', f'sm{b}'],
                  scale=1.0 / 16.0, bias=mx[b][:, h:h + 1], accum_out=sm[b][:, h:h + 1])
        k.recip(sm[b][:], sm[b][:], [f'sm{b}'], [f'sm{b}'])
        k.tt('dve', Pn[b][:], Pm[b][:], sm[b][:].unsqueeze(2).broadcast_to([128, 4, MEM]), ALU.mult,
             [f'Pm{b}', f'sm{b}'], [f'Pn{b}'])
        yield
        for h in range(4):
            for mt in range(2):
                k.tr(psT[:, (h * 2 + mt) * 128:(h * 2 + mt + 1) * 128], Pn[b][:, h, mt * 128:(mt + 1) * 128], k.identb[:],
                     [f'Pn{b}'], ['psT'])
        k.cp('act', PT[b][:], psT[:].rearrange("p (k t) -> p k t", k=8), ['psT'], [f'PT{b}'])
        for cc in range(KC):
            h = cc // 2
            pa = cc // 4
            for mt in range(2):
                k.mm(psA[:, cc * 128:(cc + 1) * 128], V[:, mt, cc * 128:(cc + 1) * 128], PT[b][:, h * 2 + mt, :],
                     mt == 0, mt == 1, [f'V{mt}{cc // 4}', f'PT{b}'], [f'psA{pa}'])
        k.cp('dve', OT[b][:, 0:4, :], psA[:, 0:512].rearrange("p (k t) -> p k t", k=4), ['psA0'], [f'OT{b}_0'])
        k.cp('act', OT[b][:, 4:8, :], psA[:, 512:1024].rearrange("p (k t) -> p k t", k=4), ['psA1'], [f'OT{b}_1'])
        yield
        for cg in range(2):
            for cc in range(KC):
                k.mm(psX[:, cg * 512:(cg + 1) * 512], OT[b][:, cc, :], Wo[:, cc, cg * 512:(cg + 1) * 512],
                     cc == 0, cc == KC - 1, [f'OT{b}_{cc // 4}', f'Wo{cc}'], [f'psX{cg}'])
        post_norm_res(k, [psX[:, 0:512], psX[:, 512:1024]], ['psX0', 'psX1'], ht[hb], f'ht{hb}',
                      g3bc, 'g3bc', [tmp[0][:], tmp[1][:]], ['tmp0', 'tmp1'], ss2[b % 2], rstd2[b % 2][:], junk, f'pn{b % 2}')
        k.dma('pool', hout[rows, :], ht[hb][:], r=[f'ht{hb}'], final=True)

    pipeline(tile, NTOK // 128)
    return k.finish()


def build_A2(NTOK, NC, fm, NF, k=None):
    k = k or K()
    NB = NTOK // 512
    x = k.din("x", [NTOK, D])
    gain = k.din("gain", [D])
    W = k.din("W", [D, NC])
    ident_d = k.din("ident", [128, 128])
    out = k.dout("out", [NTOK, NC])
    outT = k.dout("outT", [NF, NTOK])
    k.consts(ident_d)
    gc = k.gain_cols("gc", gain)
    Wb = k.load_weight("Wb", W, KC, NC, gcol=gc, gkey='gc', stage_cols=1408)
    cgs = [(c0, min(512, NC - c0)) for c0 in range(0, NC, 512)]
    xt = [k.sb(f"xt{i}", [128, D]) for i in range(2)]
    xn = [k.sb(f"xn{i}", [128, D], BF16) for i in range(2)]
    xT = [k.sb(f"xT{i}", [128, KC, 512], BF16) for i in range(2)]
    ot = [k.sb(f"ot{i}", [128, NC]) for i in range(2)]
    ft = [k.sb(f"ft{i}", [128, 512]) for i in range(2)]
    junk = k.sb("junk", [128, D], BF16)
    ss = [k.sb(f"ss{i}", [128, 1]) for i in range(2)]
    rstd = [k.sb(f"rstd{i}", [128, 1]) for i in range(2)]
    psT = k.ps("psT", [128, D], BF16)
    psO = [k.ps(f"psO{i}", [128, 512]) for i in range(4)]
    psF = [k.ps(f"psF{i}", [128, 512]) for i in range(2)]
    no = 0
    nf = 0
    for blk in range(NB):
        xb = blk % 2
        for tt in range(4):
            i = blk * 4 + tt
            b = i % 2
            k.dma('sp', xt[b][:], x[i * 128:(i + 1) * 128, :], w=[f'xt{b}'])
            norm_T(k, xt[b][:], f'xt{b}', xn[b][:], f'xn{b}', xT[xb][:, :, tt * 128:(tt + 1) * 128], f'xT{xb}', psT[:], 'psT',
                   ss[b][:], rstd[b][:], junk[:], f'n{b}')
        for tt in range(4):
            i = blk * 4 + tt
            b = i % 2
            for ci, (c0, cw) in enumerate(cgs):
                pb = no % 4
                no += 1
                for kc in range(KC):
                    k.mm(psO[pb][:, 0:cw], xT[xb][:, kc, tt * 128:(tt + 1) * 128], Wb[:, kc, c0:c0 + cw], kc == 0, kc == KC - 1,
                         [f'xT{xb}', f'Wb{kc}'], [f'psO{pb}'])
                k.cp('dve' if pb % 2 == 0 else 'act', ot[b][:, c0:c0 + cw], psO[pb][:, 0:cw], [f'psO{pb}'], [f'ot{b}_{pb % 2}'])
            k.dma('pool', out[i * 128:(i + 1) * 128, :], ot[b][:], r=[f'ot{b}_0', f'ot{b}_1'], final=True)
        for (c0, cw, r0) in fm:
            pf = nf % 2
            nf += 1
            for kc in range(KC):
                k.mm(psF[pf][0:cw, :], Wb[:, kc, c0:c0 + cw], xT[xb][:, kc, :], kc == 0, kc == KC - 1,
                     [f'Wb{kc}', f'xT{xb}'], [f'psF{pf}'])
            k.cp('dve' if pf == 0 else 'act', ft[pf][0:cw, :], psF[pf][0:cw, :], [f'psF{pf}'], [f'ft{pf}'])
            k.dma('pool', outT[r0:r0 + cw, blk * 512:(blk + 1) * 512], ft[pf][0:cw, :], r=[f'ft{pf}'], final=True)
    return k.finish()


def gen_GLA(L, k):
    NT = L // 128
    qT = k.din("qT", [128, L])
    kT = k.din("kT", [128, L])
    ktok = k.din("ktok", [L, 128])
    v = k.din("v", [L, 256])
    gate = k.din("gate", [L, 256])
    dlrT = k.din("dlrT", [16, L])
    w2 = k.din("w2", [16, 128])
    bdec = k.din("bdec", [1, 128])
    gn = k.din("gn", [256])
    triu_d = k.din("triu", [128, 128])
    trigt_d = k.din("trigt", [128, 128])
    oa = k.dout("oa", [L, 256])

    triu = k.sb("triu_s", [128, 128])
    trigt = k.sb("trigt_s", [128, 128])
    k.dma('sp', triu[:], triu_d, w=['triu'])
    k.dma('sp', trigt[:], trigt_d, w=['trigt'])
    w2s = k.sb("w2s", [16, 128])
    k.dma('sp', w2s[:], w2, w=['w2s'])
    bds = k.sb("bds", [1, 128])
    k.dma('sp', bds[:], bdec, w=['bds'])
    ones1 = k.sb("ones1", [1, 128])
    k.memset('dve', ones1[:], 1.0, ['ones1'])
    gnbc = k.bcast_row("gnbc", gn, 256)
    S = k.sb("S", [128, 128])
    k.memset('dve', S[:], 0.0, ['S'])
    NB = 2
    qTt = [k.sb(f"qTt{i}", [128, 128]) for i in range(NB)]
    kTt = [k.sb(f"kTt{i}", [128, 128]) for i in range(NB)]
    kt = [k.sb(f"kt{i}", [128, 128]) for i in range(NB)]
    vt = [k.sb(f"vt{i}", [128, 256]) for i in range(NB)]
    gt = [k.sb(f"gt{i}", [128, 256]) for i in range(NB)]
    dt_ = [k.sb(f"dt{i}", [16, 128]) for i in range(NB)]
    la = k.sb("la", [128, 128])
    EqT = k.sb("EqT", [128, 128])
    EkT = k.sb("EkT", [128, 128])
    Eks = k.sb("Eks", [128, 128])
    qin = k.sb("qin", [128, 128])
    kin = k.sb("kin", [128, 128])
    kst = k.sb("kst", [128, 128])
    sc = [k.sb(f"sc{i}", [128, 128]) for i in range(2)]
    osb = k.sb("osb", [128, 256])
    junk = k.sb("junk", [128, 128])
    ss = k.sb("ss", [128, 2])
    rs = k.sb("rs", [128, 2])
    sg = k.sb("sg", [128, 256])
    ot = [k.sb(f"ot{i}", [128, 256]) for i in range(NB)]
    psA = k.ps("psA", [128, 512])
    psB = k.ps("psB", [128, 512])
    psC = k.ps("psC", [128, 512])
    for i in range(NT):
        b = i % NB
        rows = slice(i * 128, (i + 1) * 128)
        k.dma('sp', qTt[b][:], qT[:, rows], w=[f'qTt{b}'])
        k.dma('sp', kTt[b][:], kT[:, rows], w=[f'kTt{b}'])
        k.dma('sp', kt[b][:], ktok[rows, :], w=[f'kt{b}'])
        k.dma('sp', vt[b][:], v[rows, :], w=[f'vt{b}'])
        k.dma('sp', gt[b][:], gate[rows, :], w=[f'gt{b}'])
        k.dma('sp', dt_[b][:], dlrT[:, rows], w=[f'dt{b}'])
        k.mm(psA[:, 0:128], dt_[b][:], w2s[:], True, False, [f'dt{b}', 'w2s'], ['psA'])
        k.mm(psA[:, 0:128], ones1[:], bds[:], False, True, ['ones1', 'bds'], ['psA'])
        k.act(la[:], psA[:, 0:128], AF.Exp, ['psA'], ['la'], scale=-1.0)
        k.act(la[:], la[:], AF.Ln, ['la'], ['la'], bias=1.0)
        k.ts('dve', la[:], la[:], -1.0 / 16.0, None, ALU.mult, None, ['la'], ['la'])
        k.mm(psA[:, 128:256], la[:], triu[:], True, True, ['la', 'triu'], ['psA'])
        k.mm(psA[:, 256:384], trigt[:], la[:], True, True, ['la', 'trigt'], ['psA'])
        k.act(EqT[:], psA[:, 128:256], AF.Exp, ['psA'], ['EqT'])
        k.act(EkT[:], psA[:, 128:256], AF.Exp, ['psA'], ['EkT'], scale=-1.0)
        k.act(Eks[:], psA[:, 256:384], AF.Exp, ['psA'], ['Eks'])
        k.stt(qin[:], qTt[b][:], 0.125, EqT[:], ALU.mult, ALU.mult, [f'qTt{b}', 'EqT'], ['qin'])
        k.tt('pool', kin[:], kTt[b][:], EkT[:], ALU.mult, [f'kTt{b}', 'EkT'], ['kin'])
        k.tt('pool', kst[:], kt[b][:], Eks[:], ALU.mult, [f'kt{b}', 'Eks'], ['kst'])
        for h in range(2):
            hp = slice(h * 64, (h + 1) * 64)
            k.mm(psB[:, h * 128:(h + 1) * 128], kin[hp, :], qin[hp, :], True, True, ['kin', 'qin'], ['psB'])
            k.tt('dve', sc[h][:], psB[:, h * 128:(h + 1) * 128], triu[:], ALU.mult, ['psB', 'triu'], [f'sc{h}'])
            k.mm(psC[:, h * 128:(h + 1) * 128], sc[h][:], vt[b][:, h * 128:(h + 1) * 128], True, False,
                 [f'sc{h}', f'vt{b}'], ['psC'])
            k.mm(psC[:, h * 128:(h + 1) * 128], qin[hp, :], S[hp, :], False, True, ['qin', 'S'], ['psC'])
        k.mm(psC[:, 256:512], kst[:], vt[b][:], True, True, ['kst', f'vt{b}'], ['psC'])
        for h in range(2):
            hp = slice(h * 64, (h + 1) * 64)
            k.stt(S[hp, :], S[hp, :], EqT[hp, 127:128], psC[hp, 256 + h * 128:256 + (h + 1) * 128], ALU.mult, ALU.add,
                  ['S', 'EqT', 'psC'], ['S'])
        for h in range(2):
            k.act(junk[:], psC[:, h * 128:(h + 1) * 128], AF.Square, ['psC'], ['junk', 'ss'], accum_out=ss[:, h:h + 1])
        k.ts('dve', rs[:], ss[:], 1.0 / 128.0, EPS, ALU.mult, ALU.add, ['ss'], ['rs'])
        k.act(rs[:], rs[:], AF.Ln, ['rs'], ['rs'])
        k.act(rs[:], rs[:], AF.Exp, ['rs'], ['rs'], scale=-0.5)
        k.act(sg[:], gt[b][:], AF.Exp, [f'gt{b}'], ['sg'], scale=-1.0)
        k.ts('dve', sg[:], sg[:], 1.0, None, ALU.add, None, ['sg'], ['sg'])
        k.recip(sg[:], sg[:], ['sg'], ['sg'])
        k.tt('pool', sg[:], sg[:], gt[b][:], ALU.mult, ['sg', f'gt{b}'], ['sg'])
        for h in range(2):
            hs = slice(h * 128, (h + 1) * 128)
            k.stt(osb[:, hs], psC[:, hs], rs[:, h:h + 1], gnbc[:, hs], ALU.mult, ALU.mult, ['psC', 'rs', 'gnbc'], ['osb'])
        k.tt('pool', ot[b][:], osb[:], sg[:], ALU.mult, ['osb', 'sg'], [f'ot{b}'])
        k.dma('pool', oa[rows, :], ot[b][:], r=[f'ot{b}'], final=True)
        yield


def build_GLA(L, k=None):
    k = k or K()
    for _ in gen_GLA(L, k):
        pass
    return k.finish()


TWO_PI = 2.0 * math.pi
C1 = 6.28125
C2 = TWO_PI - 6.28125
PI_LO = 3.1415925


def range_sincos(k, x, xkey, shape, s_out, c_out, skey, ckey, pfx):
    if not hasattr(k, 'rr_cache'):
        k.rr_cache = {}
    if pfx not in k.rr_cache:
        k.rr_cache[pfx] = (k.sb(pfx + "kf", shape), k.sb(pfx + "ki", shape, I32), k.sb(pfx + "r", shape), k.sb(pfx + "m", shape))
    kf, ki, r, m = k.rr_cache[pfx]
    a = lambda t: t[:]
    K1, K2, K3, K4 = pfx + 'kf', pfx + 'ki', pfx + 'r', pfx + 'm'
    k.ts('dve', a(kf), x, 1.0 / TWO_PI, None, ALU.mult, None, [xkey], [K1])
    k.cp('dve', a(ki), a(kf), [K1], [K2])
    k.cp('dve', a(kf), a(ki), [K2], [K1])
    k.stt(a(r), a(kf), -C1, x, ALU.mult, ALU.add, [K1, xkey], [K3])
    k.stt(a(r), a(kf), -C2, a(r), ALU.mult, ALU.add, [K1, K3], [K3])
    k.ts('dve', a(m), a(r), math.pi, -TWO_PI, ALU.is_gt, ALU.mult, [K3], [K4])
    k.tt('dve', a(r), a(r), a(m), ALU.add, [K3, K4], [K3])
    k.ts('dve', a(m), a(r), -math.pi, TWO_PI, ALU.is_lt, ALU.mult, [K3], [K4])
    k.tt('dve', a(r), a(r), a(m), ALU.add, [K3, K4], [K3])
    k.ts('dve', a(kf), a(r), PI_LO, -PI_LO, ALU.min, ALU.max, [K3], [K1])
    k.act(s_out, a(kf), AF.Sin, [K1], [skey])
    k.ts('dve', a(r), a(r), math.pi / 2, None, ALU.add, None, [K3], [K3])
    k.ts('dve', a(m), a(r), math.pi, -TWO_PI, ALU.is_gt, ALU.mult, [K3], [K4])
    k.tt('dve', a(r), a(r), a(m), ALU.add, [K3, K4], [K3])
    k.ts('dve', a(kf), a(r), PI_LO, -PI_LO, ALU.min, ALU.max, [K3], [K1])
    k.act(c_out, a(kf), AF.Sin, [K1], [ckey])


def build_S5(L, k=None):
    k = k or K()
    NT = L // 128
    NS = 1024
    uT = k.din("uT", [256, L])
    u = k.din("u", [L, 256])
    lam_re = k.din("lam_re", [NS])
    lam_im = k.din("lam_im", [NS])
    lstep = k.din("lstep", [NS])
    Bre = k.din("Bre", [2, 128, 512])
    Bim = k.din("Bim", [2, 128, 512])
    Cre = k.din("Cre", [8, 128, 32])
    Cim = k.din("Cim", [8, 128, 32])
    dsk = k.din("dsk", [256])
    triu_d = k.din("triu", [128, 128])
    iop_d = k.din("iota_p", [128, 1])
    iof_d = k.din("iota_f", [128, 128])
    y = k.dout("y", [L, 256])

    triu = k.sb("triu_s", [128, 128])
    k.dma('sp', triu[:], triu_d, w=['triu'])
    iop = k.sb("iop", [128, 1])
    k.dma('sp', iop[:], iop_d, w=['iop'])
    negp = k.sb("negp", [128, 1])
    k.ts('dve', negp[:], iop[:], -1.0, None, ALU.mult, None, ['iop'], ['negp'])
    iof = k.sb("iof", [128, 128])
    k.dma('sp', iof[:], iof_d, w=['iof'])
    dbc = k.bcast_row("dbc", dsk, 256)
    R = [128, NS]
    lr = k.bcast_row("lr", lam_re, NS)
    li = k.bcast_row("li", lam_im, NS)
    dl = k.bcast_row("dl", lstep, NS)
    k.ts('dve', lr[:], lr[:], -1e-4, None, ALU.min, None, ['lr'], ['lr'])
    k.act(dl[:], dl[:], AF.Exp, ['dl'], ['dl'])
    a_ = k.sb("a_", R)
    th = k.sb("th", R)
    k.tt('dve', a_[:], lr[:], dl[:], ALU.mult, ['lr', 'dl'], ['a_'])
    k.tt('dve', th[:], li[:], dl[:], ALU.mult, ['li', 'dl'], ['th'])
    sn = k.sb("sn", R)
    cs = k.sb("cs", R)
    range_sincos(k, th[:], 'th', R, sn[:], cs[:], 'sn', 'cs', 'rr_')
    ea = k.sb("ea", R)
    k.act(ea[:], a_[:], AF.Exp, ['a_'], ['ea'])
    nr = k.sb("nr", R)
    ni = k.sb("ni", R)
    k.tt('dve', nr[:], ea[:], cs[:], ALU.mult, ['ea', 'cs'], ['nr'])
    k.ts('dve', nr[:], nr[:], -1.0, None, ALU.add, None, ['nr'], ['nr'])
    k.tt('dve', ni[:], ea[:], sn[:], ALU.mult, ['ea', 'sn'], ['ni'])
    den = k.sb("den", R)
    t0 = k.sb("t0", R)
    k.tt('dve', den[:], lr[:], lr[:], ALU.mult, ['lr'], ['den'])
    k.tt('dve', t0[:], li[:], li[:], ALU.mult, ['li'], ['t0'])
    k.tt('dve', den[:], den[:], t0[:], ALU.add, ['den', 't0'], ['den'])
    k.recip(den[:], den[:], ['den'], ['den'])
    gr = k.sb("gr", R)
    gi = k.sb("gi", R)
    k.tt('dve', gr[:], nr[:], lr[:], ALU.mult, ['nr', 'lr'], ['gr'])
    k.tt('dve', t0[:], ni[:], li[:], ALU.mult, ['ni', 'li'], ['t0'])
    k.tt('dve', gr[:], gr[:], t0[:], ALU.add, ['gr', 't0'], ['gr'])
    k.tt('dve', gr[:], gr[:], den[:], ALU.mult, ['gr', 'den'], ['gr'])
    k.tt('dve', gi[:], ni[:], lr[:], ALU.mult, ['ni', 'lr'], ['gi'])
    k.tt('dve', t0[:], nr[:], li[:], ALU.mult, ['nr', 'li'], ['t0'])
    k.tt('dve', gi[:], gi[:], t0[:], ALU.subtract, ['gi', 't0'], ['gi'])
    k.tt('dve', gi[:], gi[:], den[:], ALU.mult, ['gi', 'den'], ['gi'])
    Br = k.sb("Br", [128, 2, 512])
    Bi = k.sb("Bi", [128, 2, 512])
    BBr = k.sb("BBr", [128, 2, 512])
    BBi = k.sb("BBi", [128, 2, 512])
    for hc in range(2):
        k.dma('sp', Br[:, hc, :], Bre[hc], w=[f'Br{hc}'])
        k.dma('sp', Bi[:, hc, :], Bim[hc], w=[f'Bi{hc}'])
    grv = gr[:].rearrange("p (h n) -> p h n", h=2)
    giv = gi[:].rearrange("p (h n) -> p h n", h=2)
    t0v = t0[:].rearrange("p (h n) -> p h n", h=2)
    BK = ['Br0', 'Br1', 'Bi0', 'Bi1']
    k.tt('dve', BBr[:], grv, Br[:], ALU.mult, ['gr'] + BK, ['BBr'])
    k.tt('dve', t0v, giv, Bi[:], ALU.mult, ['gi'] + BK, ['t0'])
    k.tt('dve', BBr[:], BBr[:], t0v, ALU.subtract, ['BBr', 't0'], ['BBr'])
    k.tt('dve', BBi[:], grv, Bi[:], ALU.mult, ['gr'] + BK, ['BBi'])
    k.tt('dve', t0v, giv, Br[:], ALU.mult, ['gi'] + BK, ['t0'])
    k.tt('dve', BBi[:], BBi[:], t0v, ALU.add, ['BBi', 't0'], ['BBi'])
    ang = k.sb("ang", R)
    k.ts('dve', ang[:], th[:], iop[:, 0:1], None, ALU.mult, None, ['th', 'iop'], ['ang'])
    Pr = k.sb("Pr", R)
    Pi = k.sb("Pi", R)
    range_sincos(k, ang[:], 'ang', R, sn[:], cs[:], 'sn', 'cs', 'rr_')
    k.act(ea[:], a_[:], AF.Exp, ['a_', 'negp'], ['ea'], scale=negp[:, 0:1])
    k.tt('dve', Pr[:], ea[:], cs[:], ALU.mult, ['ea', 'cs'], ['Pr'])
    k.stt(Pi[:], ea[:], -1.0, sn[:], ALU.mult, ALU.mult, ['ea', 'sn'], ['Pi'])
    Cs = [128, 8]
    lrc = k.sb("lrc", Cs)
    lic = k.sb("lic", Cs)
    dlc = k.sb("dlc", Cs)
    cv = lambda d: d.rearrange("(blk p) -> p blk", p=128)
    k.dma('sp', lrc[:], cv(lam_re), w=['lrc'], allow_slow_non_contiguous=True)
    k.dma('sp', lic[:], cv(lam_im), w=['lic'], allow_slow_non_contiguous=True)
    k.dma('sp', dlc[:], cv(lstep), w=['dlc'], allow_slow_non_contiguous=True)
    k.ts('dve', lrc[:], lrc[:], -1e-4, None, ALU.min, None, ['lrc'], ['lrc'])
    k.act(dlc[:], dlc[:], AF.Exp, ['dlc'], ['dlc'])
    ac = k.sb("ac", Cs)
    thc = k.sb("thc", Cs)
    k.tt('dve', ac[:], lrc[:], dlc[:], ALU.mult, ['lrc', 'dlc'], ['ac'])
    k.tt('dve', thc[:], lic[:], dlc[:], ALU.mult, ['lic', 'dlc'], ['thc'])
    Qr = k.sb("Qr", [128, 8, 128])
    Qi = k.sb("Qi", [128, 8, 128])
    angv = ang[:].rearrange("p (b t) -> p b t", b=8)
    eav = ea[:].rearrange("p (b t) -> p b t", b=8)
    for blk in range(8):
        k.ts('dve', angv[:, blk, :], iof[:], thc[:, blk:blk + 1], None, ALU.mult, None, ['iof', 'thc'], ['ang'])
    range_sincos(k, ang[:], 'ang', R, sn[:], cs[:], 'sn', 'cs', 'rr_')
    for blk in range(8):
        k.act(eav[:, blk, :], iof[:], AF.Exp, ['iof', 'ac'], ['ea'], scale=ac[:, blk:blk + 1])
    k.tt('dve', Qr[:].rearrange("p b t -> p (b t)"), ea[:], cs[:], ALU.mult, ['ea', 'cs'], ['Qr'])
    k.tt('dve', Qi[:].rearrange("p b t -> p (b t)"), ea[:], sn[:], ALU.mult, ['ea', 'sn'], ['Qi'])
    a128 = k.sb("a128", Cs)
    s128 = k.sb("s128", Cs)
    c128 = k.sb("c128", Cs)
    L128r = k.sb("L128r", Cs)
    L128i = k.sb("L128i", Cs)
    k.ts('dve', a128[:], thc[:], 128.0, None, ALU.mult, None, ['thc'], ['a128'])
    range_sincos(k, a128[:], 'a128', Cs, s128[:], c128[:], 's128', 'c128', 'rc_')
    k.act(a128[:], ac[:], AF.Exp, ['ac', 's128', 'c128'], ['a128'], scale=128.0)
    k.tt('dve', L128r[:], a128[:], c128[:], ALU.mult, ['a128', 'c128'], ['L128r'])
    k.tt('dve', L128i[:], a128[:], s128[:], ALU.mult, ['a128', 's128'], ['L128i'])
    Cr = k.sb("Cr", [128, 8, 32])
    nCi = k.sb("nCi", [128, 8, 32])
    k.dma('sp', Cr[:], Cre.rearrange("b p c -> p b c"), w=['Cr'])
    k.dma('sp', nCi[:], Cim.rearrange("b p c -> p b c"), w=['nCi'])
    k.ts('dve', nCi[:], nCi[:], -1.0, None, ALU.mult, None, ['nCi'], ['nCi'])
    car_r = k.sb("car_r", Cs)
    car_i = k.sb("car_i", Cs)
    k.memset('dve', car_r[:], 0.0, ['car_r0', 'car_r1'])
    k.memset('dve', car_i[:], 0.0, ['car_i0', 'car_i1'])
    uTt = [k.sb(f"uTt{i}", [128, 2, 128]) for i in range(2)]
    ut = [k.sb(f"ut{i}", [128, 256]) for i in range(4)]
    yo = [k.sb(f"yo{i}", [128, 256]) for i in range(2)]
    def T4(nm):
        return [[k.sb(f"{nm}{p}{h}", [128, 512]) for h in range(2)] for p in range(2)]
    m1, m2, m3, m4, Xtr, Xti = T4("m1_"), T4("m2_"), T4("m3_"), T4("m4_"), T4("Xtr"), T4("Xti")
    def T3(nm):
        return [[k.sb(f"{nm}{p}{h}", [128, 4, 128]) for h in range(2)] for p in range(2)]
    Gr, Gi, Hr, Hi = T3("Gr"), T3("Gi"), T3("Hr"), T3("Hi")
    cc1 = [k.sb(f"cc1_{h}", [128, 4]) for h in range(2)]
    cc2 = [k.sb(f"cc2_{h}", [128, 4]) for h in range(2)]
    psX = [[k.ps(f"psX{h}{c}", [128, 512]) for c in range(2)] for h in range(2)]
    psY = k.ps("psY", [128, 512])
    fl = lambda t: t[:].rearrange("p b t -> p (b t)")

    def tile(i):
        b = i % 2
        rows = slice(i * 128, (i + 1) * 128)
        K_ = lambda nm, hc: f'{nm}{b}{hc}'
        for hc in range(2):
            k.dma('sp', uTt[b][:, hc, :], uT[hc * 128:(hc + 1) * 128, rows], w=[f'uTt{b}{hc}'])
        b4 = i % 4
        k.dma('sp', ut[b4][:], u[rows, :], w=[f'ut{b4}'])
        for hc in range(2):
            k.mm(psX[hc][0][:], uTt[b][:, hc, :], BBr[:, hc, :], True, True, [f'uTt{b}{hc}', 'BBr'], [f'psX{hc}0'])
            k.mm(psX[hc][1][:], uTt[b][:, hc, :], BBi[:, hc, :], True, True, [f'uTt{b}{hc}', 'BBi'], [f'psX{hc}1'])
        for hc in range(2):
            cs_ = slice(hc * 512, (hc + 1) * 512)
            k.tt('dve', m1[b][hc][:], psX[hc][0][:], Pr[:, cs_], ALU.mult, [f'psX{hc}0', 'Pr'], [K_('m1', hc)])
            k.tt('dve', m2[b][hc][:], psX[hc][1][:], Pi[:, cs_], ALU.mult, [f'psX{hc}1', 'Pi'], [K_('m2', hc)])
            k.tt('pool', Xtr[b][hc][:], m1[b][hc][:], m2[b][hc][:], ALU.subtract, [K_('m1', hc), K_('m2', hc)], [K_('Xtr', hc)])
            k.tt('dve', m3[b][hc][:], psX[hc][0][:], Pi[:, cs_], ALU.mult, [f'psX{hc}0', 'Pi'], [K_('m3', hc)])
            k.tt('dve', m4[b][hc][:], psX[hc][1][:], Pr[:, cs_], ALU.mult, [f'psX{hc}1', 'Pr'], [K_('m4', hc)])
            k.tt('pool', Xti[b][hc][:], m3[b][hc][:], m4[b][hc][:], ALU.add, [K_('m3', hc), K_('m4', hc)], [K_('Xti', hc)])
        yield
        for hc in range(2):
            for nb in range(4):
                ns = slice(nb * 128, (nb + 1) * 128)
                k.mm(psX[hc][0][:, ns], Xtr[b][hc][:, ns], triu[:], True, True, [K_('Xtr', hc), 'triu'], [f'psX{hc}0'])
                k.mm(psX[hc][1][:, ns], Xti[b][hc][:, ns], triu[:], True, True, [K_('Xti', hc), 'triu'], [f'psX{hc}1'])
        for hc in range(2):
            bs = slice(hc * 4, (hc + 1) * 4)
            k.tt('dve', Gr[b][hc][:], psX[hc][0][:].rearrange("p (b t) -> p b t", b=4),
                 car_r[:, bs].unsqueeze(2).broadcast_to([128, 4, 128]), ALU.add, [f'psX{hc}0', f'car_r{hc}'], [K_('Gr', hc)])
            k.tt('dve', Gi[b][hc][:], psX[hc][1][:].rearrange("p (b t) -> p b t", b=4),
                 car_i[:, bs].unsqueeze(2).broadcast_to([128, 4, 128]), ALU.add, [f'psX{hc}1', f'car_i{hc}'], [K_('Gi', hc)])
            gr127 = Gr[b][hc][:, :, 127]
            gi127 = Gi[b][hc][:, :, 127]
            CK = [f'cc1{hc}', f'cc2{hc}']
            k.tt('pool', cc1[hc][:], L128r[:, bs], gr127, ALU.mult, ['L128r', K_('Gr', hc)], [CK[0]])
            k.tt('pool', cc2[hc][:], L128i[:, bs], gi127, ALU.mult, ['L128i', K_('Gi', hc)], [CK[1]])
            k.tt('pool', car_r[:, bs], cc1[hc][:], cc2[hc][:], ALU.subtract, CK, [f'car_r{hc}'])
            k.tt('pool', cc1[hc][:], L128r[:, bs], gi127, ALU.mult, ['L128r', K_('Gi', hc)], [CK[0]])
            k.tt('pool', cc2[hc][:], L128i[:, bs], gr127, ALU.mult, ['L128i', K_('Gr', hc)], [CK[1]])
            k.tt('pool', car_i[:, bs], cc1[hc][:], cc2[hc][:], ALU.add, CK, [f'car_i{hc}'])
        yield
        for hc in range(2):
            bs = slice(hc * 4, (hc + 1) * 4)
            qr = Qr[:, bs, :].rearrange("p b t -> p (b t)")
            qi = Qi[:, bs, :].rearrange("p b t -> p (b t)")
            k.tt('dve', m1[b][hc][:], fl(Gr[b][hc]), qr, ALU.mult, [K_('Gr', hc), 'Qr'], [K_('m1', hc)])
            k.tt('pool', m2[b][hc][:], fl(Gi[b][hc]), qi, ALU.mult, [K_('Gi', hc), 'Qi'], [K_('m2', hc)])
            k.tt('dve', fl(Hr[b][hc]), m1[b][hc][:], m2[b][hc][:], ALU.subtract, [K_('m1', hc), K_('m2', hc)], [K_('Hr', hc)])
            k.tt('dve', m3[b][hc][:], fl(Gi[b][hc]), qr, ALU.mult, [K_('Gi', hc), 'Qr'], [K_('m3', hc)])
            k.tt('pool', m4[b][hc][:], fl(Gr[b][hc]), qi, ALU.mult, [K_('Gr', hc), 'Qi'], [K_('m4', hc)])
            k.tt('dve', fl(Hi[b][hc]), m3[b][hc][:], m4[b][hc][:], ALU.add, [K_('m3', hc), K_('m4', hc)], [K_('Hi', hc)])
        yield
        for hc in range(2):
            for nb in range(4):
                blk = hc * 4 + nb
                k.mm(psY[:, blk * 32:(blk + 1) * 32], Hr[b][hc][:, nb, :], Cr[:, blk, :], True, False, [K_('Hr', hc), 'Cr'], ['psY'])
                k.mm(psY[:, blk * 32:(blk + 1) * 32], Hi[b][hc][:, nb, :], nCi[:, blk, :], False, True, [K_('Hi', hc), 'nCi'], ['psY'])
        k.tt('pool', yo[b][:], ut[b4][:], dbc[:], ALU.mult, [f'ut{b4}', 'dbc'], [f'yo{b}'])
        k.tt('dve', yo[b][:], yo[b][:], psY[:, 0:256], ALU.add, [f'yo{b}', 'psY'], [f'yo{b}'])
        k.dma('pool', y[rows, :], yo[b][:], r=[f'yo{b}'], final=True)

    pipeline(tile, NT)
    return k.finish()


def s5_host_inputs(s, proj_u, prm):
    gs = slice(16 * s, 16 * s + 16)
    cs = slice(256 * s, 256 * s + 256)
    uc = np.ascontiguousarray(proj_u[:, cs])
    Bre = np.zeros((2, 128, 512), np.float32)
    Bim = np.zeros((2, 128, 512), np.float32)
    Cre = np.zeros((8, 128, 32), np.float32)
    Cim = np.zeros((8, 128, 32), np.float32)
    b_re, b_im = prm['s5_b_re'][gs], prm['s5_b_im'][gs]
    c_re, c_im = prm['s5_c_re'][gs], prm['s5_c_im'][gs]
    for g in range(16):
        hc, gl = g // 8, g % 8
        Bre[hc, gl * 16:(gl + 1) * 16, gl * 64:(gl + 1) * 64] = b_re[g].T
        Bim[hc, gl * 16:(gl + 1) * 16, gl * 64:(gl + 1) * 64] = b_im[g].T
        blk, g2 = g // 2, g % 2
        Cre[blk, g2 * 64:(g2 + 1) * 64, g2 * 16:(g2 + 1) * 16] = c_re[g].T
        Cim[blk, g2 * 64:(g2 + 1) * 64, g2 * 16:(g2 + 1) * 16] = c_im[g].T
    return dict(uT=np.ascontiguousarray(uc.T), u=uc,
                lam_re=np.ascontiguousarray(prm['s5_lambda_re'][gs].reshape(-1)),
                lam_im=np.ascontiguousarray(prm['s5_lambda_im'][gs].reshape(-1)),
                lstep=np.ascontiguousarray(np.repeat(prm['s5_log_step'][gs], 64)),
                Bre=Bre, Bim=Bim, Cre=Cre, Cim=Cim, dsk=np.ascontiguousarray(prm['s5_d'][cs]),
                triu=np.triu(np.ones((128, 128), np.float32)),
                iota_p=np.arange(128, dtype=np.float32).reshape(128, 1),
                iota_f=np.tile(np.arange(128, dtype=np.float32)[None], (128, 1)))


GELU_C = 1.5957691216057308


def gen_LRU(L, k):
    TT = 512
    NCH = L // TT
    xbT = k.din("xbT", [256, L])
    gateT = k.din("gateT", [256, L])
    cw_d = k.din("cw", [128, 2, 4])
    cb_d = k.din("cb", [128, 2])
    Wa_d = k.din("Wa", [2, 128, 128])
    Wx_d = k.din("Wx", [2, 128, 128])
    ba_d = k.din("ba", [128, 2])
    bx_d = k.din("bx", [128, 2])
    lam_d = k.din("lam", [128, 2])
    odT = k.dout("odT", [256, L])
    cw = k.sb("cw_s", [128, 2, 4])
    cb = k.sb("cb_s", [128, 2])
    Wa = k.sb("Wa_s", [128, 2, 128])
    Wx = k.sb("Wx_s", [128, 2, 128])
    ba = k.sb("ba_s", [128, 2])
    bx = k.sb("bx_s", [128, 2])
    c8 = k.sb("c8", [128, 2])
    k.dma('sp', cw[:], cw_d, w=['cw'])
    k.dma('sp', cb[:], cb_d, w=['cb'])
    k.dma('sp', Wa[:], Wa_d.rearrange("b p n -> p b n"), w=['Wa'])
    k.dma('sp', Wx[:], Wx_d.rearrange("b p n -> p b n"), w=['Wx'])
    k.dma('sp', ba[:], ba_d, w=['ba'])
    k.dma('sp', bx[:], bx_d, w=['bx'])
    k.dma('sp', c8[:], lam_d, w=['c8'])
    k.act(c8[:], c8[:], AF.Exp, ['c8'], ['c8'], scale=-1.0)
    k.act(c8[:], c8[:], AF.Ln, ['c8'], ['c8'], bias=1.0)
    k.ts('dve', c8[:], c8[:], -8.0, None, ALU.mult, None, ['c8'], ['c8'])
    hlast = k.sb("hlast", [128, 2])
    k.memset('dve', hlast[:], 0.0, ['hlast0', 'hlast1'])
    xh = [k.sb(f"xh{i}", [128, TT + 3]) for i in range(2)]
    gt = [k.sb(f"gt{i}", [128, TT]) for i in range(2)]
    xc = k.sb("xc", [128, TT])
    r = k.sb("r", [128, TT])
    ig = k.sb("ig", [128, TT])
    a = k.sb("a", [128, TT])
    a2 = k.sb("a2", [128, TT])
    bt = k.sb("bt", [128, TT])
    h = k.sb("h", [128, TT])
    g2 = k.sb("g2", [128, TT])
    ge = k.sb("ge", [128, TT])
    ot = [k.sb(f"ot{i}", [128, TT]) for i in range(2)]
    psR = k.ps("psR", [128, TT])
    psI = k.ps("psI", [128, TT])
    n = 0
    for c in range(NCH):
        for pb in range(2):
            b = n % 2
            n += 1
            prow = slice(pb * 128, (pb + 1) * 128)
            if c == 0:
                k.memset('pool', xh[b][:, 0:3], 0.0, [f'xh{b}h'])
                k.dma('sp', xh[b][:, 3:TT + 3], xbT[prow, 0:TT], w=[f'xh{b}'])
            else:
                k.dma('sp', xh[b][:, 0:TT + 3], xbT[prow, c * TT - 3:(c + 1) * TT], w=[f'xh{b}', f'xh{b}h'])
            k.dma('sp', gt[b][:], gateT[prow, c * TT:(c + 1) * TT], w=[f'gt{b}'])
            xk = [f'xh{b}', f'xh{b}h']
            k.ts('dve', xc[:], xh[b][:, 3:TT + 3], cw[:, pb, 3:4], cb[:, pb:pb + 1], ALU.mult, ALU.add, xk + ['cw', 'cb'], ['xc'])
            for j in (2, 1, 0):
                k.stt(xc[:], xh[b][:, j:j + TT], cw[:, pb, j:j + 1], xc[:], ALU.mult, ALU.add, xk + ['cw', 'xc'], ['xc'])
            k.mm(psR[:], Wa[:, pb, :], xc[:], True, True, ['Wa', 'xc'], ['psR'])
            k.mm(psI[:], Wx[:, pb, :], xc[:], True, True, ['Wx', 'xc'], ['psI'])
            k.act(r[:], psR[:], AF.Sigmoid, ['psR', 'ba'], ['r'], bias=ba[:, pb:pb + 1])
            k.act(ig[:], psI[:], AF.Sigmoid, ['psI', 'bx'], ['ig'], bias=bx[:, pb:pb + 1])
            k.act(a[:], r[:], AF.Exp, ['r', 'c8'], ['a'], scale=c8[:, pb:pb + 1])
            k.act(a2[:], a[:], AF.Square, ['a'], ['a2'])
            k.act(a2[:], a2[:], AF.Sqrt, ['a2'], ['a2'], scale=-1.0, bias=1.0)
            k.tt('pool', bt[:], ig[:], xc[:], ALU.mult, ['ig', 'xc'], ['bt'])
            k.tt('pool', bt[:], bt[:], a2[:], ALU.mult, ['bt', 'a2'], ['bt'])
            k.P.op('dve', lambda e, pb=pb: e.tensor_tensor_scan(out=h[:], data0=a[:], data1=bt[:], initial=hlast[:, pb:pb + 1],
                                                                op0=ALU.mult, op1=ALU.add),
                   reads=['a', 'bt', f'hlast{pb}'], writes=['h'])
            k.cp('dve', hlast[:, pb:pb + 1], h[:, TT - 1:TT], ['h'], [f'hlast{pb}'])
            k.act(g2[:], gt[b][:], AF.Square, [f'gt{b}'], ['g2'])
            k.act(g2[:], g2[:], AF.Copy, ['g2'], ['g2'], scale=0.044715, bias=1.0)
            k.tt('pool', g2[:], g2[:], gt[b][:], ALU.mult, ['g2', f'gt{b}'], ['g2'])
            k.act(g2[:], g2[:], AF.Sigmoid, ['g2'], ['g2'], scale=GELU_C)
            k.tt('pool', ge[:], g2[:], gt[b][:], ALU.mult, ['g2', f'gt{b}'], ['ge'])
            k.tt('dve', ot[b][:], h[:], ge[:], ALU.mult, ['h', 'ge'], [f'ot{b}'])
            k.dma('pool', odT[prow, c * TT:(c + 1) * TT], ot[b][:], r=[f'ot{b}'], final=True)
            yield


def build_LRU(L, k=None):
    k = k or K()
    for _ in gen_LRU(L, k):
        pass
    return k.finish()


def lru_host_inputs(s, xb, gate, prm):
    cs = slice(256 * s, 256 * s + 256)
    col = lambda v: np.ascontiguousarray(v[cs].reshape(2, 128).T)
    Wa = np.zeros((2, 128, 128), np.float32)
    Wx = np.zeros((2, 128, 128), np.float32)
    for pb in range(2):
        for bl in range(2):
            blk = 4 * s + 2 * pb + bl
            Wa[pb, bl * 64:(bl + 1) * 64, bl * 64:(bl + 1) * 64] = prm['lru_w_a'][blk]
            Wx[pb, bl * 64:(bl + 1) * 64, bl * 64:(bl + 1) * 64] = prm['lru_w_x'][blk]
    cw = np.ascontiguousarray(prm['lru_conv_w'][:, cs].reshape(4, 2, 128).transpose(2, 1, 0))
    return dict(xbT=np.ascontiguousarray(xb[:, cs].T), gateT=np.ascontiguousarray(gate[:, cs].T), cw=cw,
                cb=col(prm['lru_conv_b']), Wa=Wa, Wx=Wx, ba=col(prm['lru_b_a']), bx=col(prm['lru_b_x']),
                lam=col(prm['lru_lambda']))


GN_EPS = 64e-5
NLEV = 5


def build_RWKV(L, k=None, NH=4, fr=False, CH=64):
    k = k or K()
    NT = L // 128
    W = NH * 64
    NG = NH // 4
    FR = mybir.dt.float32r if fr else F32
    rd = (lambda ap: ap.bitcast(F32)) if fr else (lambda ap: ap)
    NCK = 128 // CH
    nlev = 5 if CH == 64 else 6
    frc = fr and CH == 128
    FRC = mybir.dt.float32r if frc else F32
    rdc = (lambda ap: ap.bitcast(F32)) if frc else (lambda ap: ap)
    lhc = (lambda ap: ap) if frc else rd
    prkv = [k.din(nm, [L, W]) for nm in ("pr", "pk", "pv")]
    mu1 = k.din("mu1", [3 * W])
    pls = [k.din("plw", [64, L]), k.din("pla", [64, L]), k.din("plg", [128, L])]
    mul = k.din("mul", [128, 3])
    w2 = k.din("w2", [64, W])
    a2 = k.din("a2", [64, W])
    g2 = k.din("g2", [128, W])
    vecs = k.din("vecs", [7, W])
    ident_d = k.din("ident", [128, 128])
    triw_d = k.din("triw", [3, 128, 128])
    mask5_d = k.din("mask5", [128, 640])
    rowm_d = k.din("rowm", [128, 2])
    oc = k.dout("oc", [L, W])

    k.consts(ident_d)
    triw = k.sb("triw_s", [128, 3, 128])
    k.dma('sp', triw[:], triw_d.rearrange("a p n -> p a n"), w=['triw'])
    mask5 = k.sb("mask5_s", [128, 640])
    k.dma('sp', mask5[:], mask5_d, w=['mask5'])
    rowm = k.sb("rowm_s", [128, 2])
    k.dma('sp', rowm[:], rowm_d, w=['rowm'])
    mu1bc = k.bcast_row("mu1bc", mu1, 3 * W)
    vb = [k.bcast_row(f"vb{i}", vecs[i], W) for i in range(7)]
    w0bc, a0bc, kkbc, kabc, rkbc, lngbc, lnbbc = vb
    VK = [f"vb{i}" for i in range(7)]
    muls = k.sb("muls", [128, 3])
    k.dma('sp', muls[:], mul, w=['muls'])
    w2s = k.sb("w2s", [64, W])
    a2s = k.sb("a2s", [64, W])
    k.dma('sp', w2s[:], w2, w=['w2s'])
    k.dma('sp', a2s[:], a2, w=['a2s'])
    g2s = k.sb("g2s", [128, W])
    k.dma('sp', g2s[:], g2, w=['g2s'])
    ST = [k.sb(f"ST{i}", [64, 64], FRC) for i in range(NH)]
    zt = k.sb("zt", [128, W])
    k.memset('dve', zt[:], 0.0, ['zt'])
    for i in range(NH):
        k.cp('dve', ST[i][:], zt[0:64, 0:64], ['zt'], [f'ST{i}'])
    P1s = k.sb("P1s", [128, W], FRC)
    Us = k.sb("Us", [128, W], FRC)
    k.cp('dve', P1s[:], zt[:], ['zt'], ['P1s'])
    k.cp('dve', Us[:], zt[:], ['zt'], ['Us'])

    pt = [k.sb(f"pt{i}", [128, 3 * W]) for i in range(2)]
    pp = [k.sb(f"pp{i}", [128, 3 * W]) for i in range(2)]
    lt = [k.sb(f"lt{i}", [128, 3, 128]) for i in range(2)]
    lp = [k.sb(f"lp{i}", [128, 3, 128]) for i in range(2)]
    for i_ in range(2):
        k.memset('pool', lt[i_][:], 0.0, [f'lt{i_}0', f'lt{i_}1', f'lt{i_}2'])
        k.memset('pool', lp[i_][:], 0.0, [f'lp{i_}0', f'lp{i_}1', f'lp{i_}2', f'lp{i_}z'])
    pm = k.sb("pm", [128, 3 * W])
    vr = k.sb("vr", [128, W], FR)
    lm = k.sb("lm", [128, 3, 128])
    sw = k.sb("sw", [128, W])
    av = k.sb("av", [128, W])
    gv = k.sb("gv", [128, W])
    kkr = k.sb("kkr", [128, W])
    sq = k.sb("sq", [128, W])
    s4 = k.sb("s4", [128, NH])
    rn = k.sb("rn", [128, NH])
    nkk = k.sb("nkk", [128, W])
    kmod = k.sb("kmod", [128, W])
    kka = k.sb("kka", [128, W])
    tmp = k.sb("tmp", [128, W])
    bon = k.sb("bon", [128, NH])
    E1 = k.sb("E1", [128, W])
    E2 = k.sb("E2", [128, W])
    E3 = k.sb("E3", [128, W])
    E4 = k.sb("E4", [128, W])
    E1T = k.sb("E1T", [64, NH, 128])
    At = k.sb("At", [128, W])
    Bs = k.sb("Bs", [128, W])
    Ks = k.sb("Ks", [128, W])
    Rt = k.sb("Rt", [128, W])
    Bfm = [k.sb(f"Bfm{c}", [128, W]) for c in range(2)]
    Kfm = [k.sb(f"Kfm{c}", [128, W]) for c in range(2)]
    FT = [k.sb(f"FT{h}", [64, 4, 128], FR) for h in range(NH)]
    A5 = [k.sb(f"A5_{h}", [128, 640], FR) for h in range(NH)]
    NL = [k.sb(f"NL_{h}", [128, 256], FR) for h in range(NH)]
    PQ = [k.sb(f"PQ_{h}", [128, 256], FR) for h in range(NH)]
    W1 = k.sb("W1", [128, W], FR)
    U1 = k.sb("U1", [128, W])
    ysb = k.sb("ysb", [128, W])
    yc = k.sb("yc", [128, W])
    m4 = k.sb("m4", [128, NH])
    r4 = k.sb("r4", [128, NH])
    ot = [k.sb(f"ot{i}", [128, W]) for i in range(2)]
    B = [k.ps(f"psB{i}", [128, 512]) for i in range(8)]
    bk = lambda i: f'psB{i}'
    v3 = lambda t: t.rearrange("p (h j) -> p h j", h=NH)
    bc4 = lambda t: t.unsqueeze(2).broadcast_to([128, NH, 64])

    for i in range(NT):
        b = i % 2
        rows = slice(i * 128, (i + 1) * 128)
        PK, PPK, LTK, LPK = [], [], [], []
        for q in range(3):
            cq = slice(q * W, (q + 1) * W)
            k.dma('sp', pt[b][:, cq], prkv[q][rows, :], w=[f'pt{b}{q}'])
            PK.append(f'pt{b}{q}')
            if i == 0:
                k.dma('sp', pp[b][1:128, cq], prkv[q][0:127, :], w=[f'pp{b}{q}'])
            else:
                k.dma('sp', pp[b][:, cq], prkv[q][i * 128 - 1:i * 128 + 127, :], w=[f'pp{b}{q}'])
            PPK.append(f'pp{b}{q}')
            nr = pls[q].shape[0]
            k.dma('sp', lt[b][0:nr, q, :], pls[q][:, rows], w=[f'lt{b}{q}'])
            LTK.append(f'lt{b}{q}')
            if i == 0:
                k.dma('sp', lp[b][0:nr, q, 1:128], pls[q][:, 0:127], w=[f'lp{b}{q}'])
            else:
                k.dma('sp', lp[b][0:nr, q, :], pls[q][:, i * 128 - 1:i * 128 + 127], w=[f'lp{b}{q}'])
            LPK.append(f'lp{b}{q}')
        if i == 0:
            k.memset('pool', pp[b][0:1, :], 0.0, [f'pp{b}z'])
            k.memset('pool', lp[b][:, :, 0:1], 0.0, [f'lp{b}z'])
            PPK.append(f'pp{b}z')
            LPK.append(f'lp{b}z')
        k.tt('pool', pm[:], pp[b][:], pt[b][:], ALU.subtract, PPK + PK, ['pm'])
        k.tt('pool', pm[:], pm[:], mu1bc[:], ALU.mult, ['pm', 'mu1bc'], ['pm'])
        k.tt('pool', pm[:], pm[:], pt[b][:], ALU.add, ['pm'] + PK, ['pm'])
        r_, k_, v_ = pm[:, 0:W], pm[:, W:2 * W], pm[:, 2 * W:3 * W]
        k.cp('act', vr[:], v_, ['pm'], ['vr'])
        LK = LTK + LPK
        k.tt('dve', lm[:], lp[b][:], lt[b][:], ALU.subtract, LK, ['lm'])
        for blk in range(3):
            k.stt(lm[:, blk, :], lm[:, blk, :], muls[:, blk:blk + 1], lt[b][:, blk, :], ALU.mult, ALU.add,
                  ['lm', 'muls'] + LK, ['lm'])
        k.act(lm[0:64, 0, :], lm[0:64, 0, :], AF.Tanh, ['lm'], ['lm'])
        k.act(lm[:, 2, :], lm[:, 2, :], AF.Sigmoid, ['lm'], ['lm'])
        k.mm(B[0][:, 0:W], lm[0:64, 0, :], w2s[:], True, True, ['lm', 'w2s'], [bk(0)])
        k.mm(B[1][:, 0:W], lm[0:64, 1, :], a2s[:], True, True, ['lm', 'a2s'], [bk(1)])
        k.mm(B[2][:, 0:W], lm[:, 2, :], g2s[:], True, True, ['lm', 'g2s'], [bk(2)])
        k.tt('dve', sw[:], B[0][:, 0:W], w0bc[:], ALU.add, [bk(0), VK[0]], ['sw'])
        k.act(sw[:], sw[:], AF.Sigmoid, ['sw'], ['sw'])
        k.tt('dve', av[:], B[1][:, 0:W], a0bc[:], ALU.add, [bk(1), VK[1]], ['av'])
        k.act(av[:], av[:], AF.Sigmoid, ['av'], ['av'])
        k.cp('act', gv[:], B[2][:, 0:W], [bk(2)], ['gv'])
        k.tt('pool', kkr[:], k_, kkbc[:], ALU.mult, ['pm', VK[2]], ['kkr'])
        k.tt('pool', sq[:], kkr[:], kkr[:], ALU.mult, ['kkr'], ['sq'])
        k.P.op('dve', lambda e: e.tensor_reduce(out=s4[:], in_=v3(sq[:]), axis=AX.X, op=ALU.add), reads=['sq'], writes=['s4'])
        k.act(s4[:], s4[:], AF.Sqrt, ['s4'], ['s4'])
        k.ts('dve', s4[:], s4[:], 1e-12, None, ALU.max, None, ['s4'], ['s4'])
        k.recip(rn[:], s4[:], ['s4'], ['rn'])
        k.ts('dve', rn[:], rn[:], -1.0, None, ALU.mult, None, ['rn'], ['rn'])
        k.tt('dve', v3(nkk[:]), v3(kkr[:]), bc4(rn[:]), ALU.mult, ['kkr', 'rn'], ['nkk'])
        k.stt(tmp[:], av[:], -1.0, kabc[:], ALU.add, ALU.mult, ['av', VK[3]], ['tmp'])
        k.stt(kmod[:], tmp[:], 1.0, k_, ALU.add, ALU.mult, ['tmp', 'pm'], ['kmod'])
        k.stt(kka[:], nkk[:], -1.0, av[:], ALU.mult, ALU.mult, ['nkk', 'av'], ['kka'])
        k.tt('pool', tmp[:], r_, kmod[:], ALU.mult, ['pm', 'kmod', 'tmp'], ['tmp'])
        k.tt('pool', tmp[:], tmp[:], rkbc[:], ALU.mult, ['tmp', VK[4]], ['tmp'])
        k.P.op('dve', lambda e: e.tensor_reduce(out=bon[:], in_=v3(tmp[:]), axis=AX.X, op=ALU.add), reads=['tmp'], writes=['bon'])
        k.mm(B[3][:, 0:W], triw[:, 0, :], sw[:], True, True, ['triw', 'sw'], [bk(3)])
        k.mm(B[4][:, 0:W], triw[:, 1, :], sw[:], True, True, ['triw', 'sw'], [bk(4)])
        k.mm(B[5][:, 0:W], triw[:, 2, :], sw[:], True, True, ['triw', 'sw'], [bk(5)])
        for h in range(NH):
            k.mm(B[6 + h // 4][0:64, (h % 4) * 128:(h % 4 + 1) * 128], sw[:, h * 64:(h + 1) * 64], triw[:, 0, :], True, True,
                 ['sw', 'triw'], [bk(6 + h // 4)])
        k.act(E1[:], B[3][:, 0:W], AF.Exp, [bk(3)], ['E1'])
        k.act(E2[:], B[3][:, 0:W], AF.Exp, [bk(3)], ['E2'], scale=-1.0)
        k.act(E3[:], B[4][:, 0:W], AF.Exp, [bk(4)], ['E3'])
        k.act(E4[:], B[5][:, 0:W], AF.Exp, [bk(5)], ['E4'])
        for g in range(NG):
            k.act(E1T[:, 4 * g:4 * g + 4, :].rearrange("p a t -> p (a t)"), B[6 + g][0:64, :], AF.Exp, [bk(6 + g)], ['E1T'])
        k.tt('dve', At[:], nkk[:], E3[:], ALU.mult, ['nkk', 'E3'], ['At'])
        k.tt('pool', Bs[:], kka[:], E2[:], ALU.mult, ['kka', 'E2'], ['Bs'])
        k.tt('dve', Ks[:], kmod[:], E2[:], ALU.mult, ['kmod', 'E2'], ['Ks'])
        k.tt('pool', Rt[:], r_, E1[:], ALU.mult, ['pm', 'E1'], ['Rt'])
        for c in range(NCK):
            k.stt(Bfm[c][:], kka[:], rowm[:, c:c + 1], E4[:], ALU.mult, ALU.mult, ['kka', 'E4', 'rowm'], [f'Bfm{c}'])
            k.stt(Kfm[c][:], kmod[:], rowm[:, c:c + 1], E4[:], ALU.mult, ALU.mult, ['kmod', 'E4', 'rowm'], [f'Kfm{c}'])
        HS = list(range(NH))
        for h in HS:
            cs_ = slice(h * 64, (h + 1) * 64)
            for q, (src, key) in enumerate([(At, 'At'), (Bs, 'Bs'), (Ks, 'Ks'), (Rt, 'Rt')]):
                k.tr(B[h][0:64, q * 128:(q + 1) * 128], src[:, cs_], k.identf[:], [key], [bk(h)])
        for h in HS:
            k.cp('act' if h % 2 else 'dve', FT[h][:].rearrange("p a t -> p (a t)"), B[h][0:64, :], [bk(h)], [f'FT{h}'])
        for h in HS:
            AtT, BsT, KsT, RtT = (FT[h][:, q, :] for q in range(4))
            o = lambda j: B[h][:, j * 128:(j + 1) * 128]
            k.mm(o(0), BsT, AtT, True, True, [f'FT{h}'], [bk(h)])
            k.mm(o(1), AtT, BsT, True, True, [f'FT{h}'], [bk(h)])
            k.mm(o(2), KsT, AtT, True, True, [f'FT{h}'], [bk(h)])
        for h in HS:
            k.tt('dve', A5[h][:, 0:384], B[h][:, 0:384], mask5[:, 0:384], ALU.mult, [bk(h), 'mask5'], [f'A5_{h}'])
        for h in HS:
            AtT, BsT, KsT, RtT = (FT[h][:, q, :] for q in range(4))
            k.mm(B[h][:, 0:128], BsT, RtT, True, True, [f'FT{h}'], [bk(h)])
            k.mm(B[h][:, 128:256], KsT, RtT, True, True, [f'FT{h}'], [bk(h)])
        for h in HS:
            k.tt('dve', A5[h][:, 384:640], B[h][:, 0:256], mask5[:, 384:640], ALU.mult, [bk(h), 'mask5'], [f'A5b_{h}'])
            k.cp('act', NL[h][:], rd(A5[h][:, 0:256]), [f'A5_{h}'], [f'NL_{h}'])
            k.tt('pool' if not fr else 'dve', PQ[h][:].rearrange("p (a n) -> p a n", a=2), rd(A5[h][:, 0:256]).rearrange("p (a n) -> p a n", a=2),
                 k.identf[:].unsqueeze(1).broadcast_to([128, 2, 128]), ALU.add, [f'A5_{h}', 'ident'], [f'PQ_{h}'])
        for lev in range(nlev):
            for h in HS:
                N_, L_ = NL[h][:, 0:128], NL[h][:, 128:256]
                k.mm(B[h][:, 0:128], L_, N_, True, True, [f'NL_{h}'], [bk(h)])
                k.mm(B[h][:, 128:256], N_, L_, True, True, [f'NL_{h}'], [bk(h)])
            for h in HS:
                k.cp('act', NL[h][:], B[h][:, 0:256], [bk(h)], [f'NL_{h}'])
            for h in HS:
                N_, L_ = NL[h][:, 0:128], NL[h][:, 128:256]
                P_, Q_ = PQ[h][:, 0:128], PQ[h][:, 128:256]
                k.mm(B[h][:, 256:384], Q_, N_, True, True, [f'NL_{h}', f'PQ_{h}'], [bk(h)])
                k.mm(B[h][:, 384:512], P_, L_, True, True, [f'NL_{h}', f'PQ_{h}'], [bk(h)])
            for h in HS:
                k.tt('dve', PQ[h][:], B[h][:, 256:512], rd(PQ[h][:]), ALU.add, [bk(h), f'PQ_{h}'], [f'PQ_{h}'])
        for h in range(NH):
            k.mm(B[0][:, h * 64:(h + 1) * 64], A5[h][:, 256:384], vr[:, h * 64:(h + 1) * 64], True, True, [f'A5_{h}', 'vr'], [bk(0)])
        k.cp('act', W1[:], B[0][:, 0:W], [bk(0)], ['W1'])
        for h in range(NH):
            k.mm(B[1][:, h * 64:(h + 1) * 64], PQ[h][:, 0:128], W1[:, h * 64:(h + 1) * 64], True, True,
                 [f'PQ_{h}', 'W1'], [bk(1)])
        k.cp('act', U1[:], B[1][:, 0:W], [bk(1)], ['U1'])
        vsrc = vr if frc else None
        for c in range(NCK):
            cr = slice(c * CH, (c + 1) * CH)
            for h in range(NH):
                k.mm(B[2][cr, h * 64:(h + 1) * 64], lhc(FT[h][:, 0, cr]), ST[h][:], True, True, [f'FT{h}', f'ST{h}'], [bk(2)])
            k.cp('act', P1s[cr, :], B[2][cr, 0:W], [bk(2)], ['P1s'])
            for h in range(NH):
                k.mm(B[3][cr, h * 64:(h + 1) * 64], lhc(PQ[h][:, cr]), P1s[:, h * 64:(h + 1) * 64], True, True,
                     [f'PQ_{h}', 'P1s'], [bk(3)])
            k.tt('dve', Us[cr, :], B[3][cr, 0:W], U1[cr, :], ALU.add, [bk(3), 'U1'], ['Us'])
            for h in range(NH):
                hc_ = slice(h * 64, (h + 1) * 64)
                vh = vr[:, hc_] if frc else pm[:, 2 * W + h * 64:2 * W + (h + 1) * 64]
                vk = 'vr' if frc else 'pm'
                k.mm(B[6][cr, hc_], lhc(FT[h][:, 3, cr]), ST[h][:], True, False, [f'FT{h}', f'ST{h}'], [bk(6)])
                k.mm(B[6][cr, hc_], lhc(A5[h][:, 384:512][:, cr]), Us[:, hc_], False, False, [f'A5b_{h}', 'Us'], [bk(6)])
                k.mm(B[6][cr, hc_], lhc(A5[h][:, 512:640][:, cr]), vh, False, True, [f'A5b_{h}', vk], [bk(6)])
            for h in range(NH):
                hc_ = slice(h * 64, (h + 1) * 64)
                vh = pm[:, 2 * W + h * 64:2 * W + (h + 1) * 64]
                k.mm(B[7][0:64, hc_], Bfm[c][:, hc_], rdc(Us[:, hc_]), True, False, [f'Bfm{c}', 'Us'], [bk(7)])
                k.mm(B[7][0:64, hc_], Kfm[c][:, hc_], vh, False, True, [f'Kfm{c}', 'pm'], [bk(7)])
            for h in range(NH):
                hc_ = slice(h * 64, (h + 1) * 64)
                k.stt(ST[h][:], rdc(ST[h][:]), E1T[:, h, (c + 1) * CH - 1:(c + 1) * CH], B[7][0:64, hc_], ALU.mult, ALU.add,
                      [f'ST{h}', 'E1T', bk(7)], [f'ST{h}'])
        k.cp('act', ysb[:], B[6][:, 0:W], [bk(6)], ['ysb'])
        k.P.op('dve', lambda e: e.tensor_reduce(out=m4[:], in_=v3(ysb[:]), axis=AX.X, op=ALU.add), reads=['ysb'], writes=['m4'])
        k.ts('dve', m4[:], m4[:], -1.0 / 64.0, None, ALU.mult, None, ['m4'], ['m4'])
        k.tt('dve', v3(yc[:]), v3(ysb[:]), bc4(m4[:]), ALU.add, ['ysb', 'm4'], ['yc'])
        k.tt('pool', sq[:], yc[:], yc[:], ALU.mult, ['yc'], ['sq'])
        k.P.op('dve', lambda e: e.tensor_reduce(out=r4[:], in_=v3(sq[:]), axis=AX.X, op=ALU.add), reads=['sq'], writes=['r4'])
        k.ts('dve', r4[:], r4[:], 1.0 / 64.0, GN_EPS, ALU.mult, ALU.add, ['r4'], ['r4'])
        k.act(r4[:], r4[:], AF.Sqrt, ['r4'], ['r4'])
        k.recip(r4[:], r4[:], ['r4'], ['r4'])
        k.tt('dve', v3(yc[:]), v3(yc[:]), bc4(r4[:]), ALU.mult, ['yc', 'r4'], ['yc'])
        k.tt('pool', yc[:], yc[:], lngbc[:], ALU.mult, ['yc', VK[5]], ['yc'])
        k.tt('pool', yc[:], yc[:], lnbbc[:], ALU.add, ['yc', VK[6]], ['yc'])
        k.tt('dve', v3(tmp[:]), v3(v_), bc4(bon[:]), ALU.mult, ['pm', 'bon', 'tmp'], ['tmp'])
        k.tt('pool', yc[:], yc[:], tmp[:], ALU.add, ['yc', 'tmp'], ['yc'])
        k.tt('dve', ot[b][:], yc[:], gv[:], ALU.mult, ['yc', 'gv'], [f'ot{b}'])
        k.dma('pool', oc[rows, :], ot[b][:], r=[f'ot{b}'], final=True)
    return k.finish()


def build_RWKVP(L, k=None, CH=64):
    NH, fr = 8, True
    k = k or K()
    NT = L // 128
    W = NH * 64
    NG = NH // 4
    FR = mybir.dt.float32r if fr else F32
    rd = (lambda ap: ap.bitcast(F32)) if fr else (lambda ap: ap)
    NCK = 128 // CH
    nlev = 5 if CH == 64 else 6
    frc = fr and CH == 128
    FRC = mybir.dt.float32r if frc else F32
    rdc = (lambda ap: ap.bitcast(F32)) if frc else (lambda ap: ap)
    lhc = (lambda ap: ap) if frc else rd
    prkv = [k.din(nm, [L, W]) for nm in ("pr", "pk", "pv")]
    mu1 = k.din("mu1", [3 * W])
    pls = [k.din("plw", [64, L]), k.din("pla", [64, L]), k.din("plg", [128, L])]
    mul = k.din("mul", [128, 3])
    w2 = k.din("w2", [64, W])
    a2 = k.din("a2", [64, W])
    g2 = k.din("g2", [128, W])
    vecs = k.din("vecs", [7, W])
    ident_d = k.din("ident", [128, 128])
    triw_d = k.din("triw", [3, 128, 128])
    mask5_d = k.din("mask5", [128, 640])
    rowm_d = k.din("rowm", [128, 2])
    oc = k.dout("oc", [L, W])

    k.consts(ident_d)
    triw = k.sb("triw_s", [128, 3, 128])
    k.dma('sp', triw[:], triw_d.rearrange("a p n -> p a n"), w=['triw'])
    mask5 = k.sb("mask5_s", [128, 640])
    k.dma('sp', mask5[:], mask5_d, w=['mask5'])
    rowm = k.sb("rowm_s", [128, 2])
    k.dma('sp', rowm[:], rowm_d, w=['rowm'])
    mu1bc = k.bcast_row("mu1bc", mu1, 3 * W)
    vb = [k.bcast_row(f"vb{i}", vecs[i], W) for i in range(7)]
    w0bc, a0bc, kkbc, kabc, rkbc, lngbc, lnbbc = vb
    VK = [f"vb{i}" for i in range(7)]
    muls = k.sb("muls", [128, 3])
    k.dma('sp', muls[:], mul, w=['muls'])
    w2s = k.sb("w2s", [64, W])
    a2s = k.sb("a2s", [64, W])
    k.dma('sp', w2s[:], w2, w=['w2s'])
    k.dma('sp', a2s[:], a2, w=['a2s'])
    g2s = k.sb("g2s", [128, W])
    k.dma('sp', g2s[:], g2, w=['g2s'])
    ST = [k.sb(f"ST{i}", [64, 64], FRC) for i in range(NH)]
    zt = k.sb("zt", [128, W])
    k.memset('dve', zt[:], 0.0, ['zt'])
    for i in range(NH):
        k.cp('dve', ST[i][:], zt[0:64, 0:64], ['zt'], [f'ST{i}'])
    P1s = k.sb("P1s", [128, W], FRC)
    Us = k.sb("Us", [128, W], FRC)
    k.cp('dve', P1s[:], zt[:], ['zt'], ['P1s'])
    k.cp('dve', Us[:], zt[:], ['zt'], ['Us'])

    pt = [k.sb("pt0", [128, 3 * W])] * 2
    pp = [k.sb("pp0", [128, 3 * W])] * 2
    lt = [k.sb("lt0", [128, 3, 128])] * 2
    lp = [k.sb("lp0", [128, 3, 128])] * 2
    k.memset('pool', lt[0][:], 0.0, ['lt0', 'lt1', 'lt2'])
    k.memset('pool', lp[0][:], 0.0, ['lp0', 'lp1', 'lp2', 'lpz'])
    pm2 = [k.sb(f"pm{i_}", [128, 3 * W]) for i_ in range(2)]
    vr2 = [k.sb(f"vr{i_}", [128, W], FR) for i_ in range(2)]
    lm2 = [k.sb(f"lm{i_}", [128, 3, 128]) for i_ in range(2)]
    sw = k.sb("sw", [128, W])
    av = k.sb("av", [128, W])
    gv2 = [k.sb(f"gv{i_}", [128, W]) for i_ in range(2)]
    kkr = k.sb("kkr", [128, W])
    sq = k.sb("sq", [128, W])
    s4 = k.sb("s4", [128, NH])
    rn = k.sb("rn", [128, NH])
    nkk = k.sb("nkk", [128, W])
    kmod = k.sb("kmod", [128, W])
    kka = k.sb("kka", [128, W])
    tmp = k.sb("tmp", [128, W])
    bon2 = [k.sb(f"bon{i_}", [128, NH]) for i_ in range(2)]
    E1 = k.sb("E1", [128, W])
    E2 = k.sb("E2", [128, W])
    E3 = k.sb("E3", [128, W])
    E4 = k.sb("E4", [128, W])
    E1T2 = [k.sb(f"E1T{i_}", [64, NH, 128]) for i_ in range(2)]
    At2 = [k.sb(f"At{i_}", [128, W]) for i_ in range(2)]
    Bs2 = [k.sb(f"Bs{i_}", [128, W]) for i_ in range(2)]
    Ks2 = [k.sb(f"Ks{i_}", [128, W]) for i_ in range(2)]
    Rt2 = [k.sb(f"Rt{i_}", [128, W]) for i_ in range(2)]
    Bfm2 = [[k.sb(f"Bfm{p_}{c}", [128, W]) for c in range(NCK)] for p_ in range(2)]
    Kfm2 = [[k.sb(f"Kfm{p_}{c}", [128, W]) for c in range(NCK)] for p_ in range(2)]
    sqp = k.sb("sqp", [128, W])
    tmpp = k.sb("tmpp", [128, W])
    FT = [k.sb(f"FT{h}", [64, 4, 128], FR) for h in range(NH)]
    A5 = [k.sb(f"A5_{h}", [128, 640], FR) for h in range(NH)]
    NL = [k.sb(f"NL_{h}", [128, 256], FR) for h in range(NH)]
    PQ = [k.sb(f"PQ_{h}", [128, 128], FR) for h in range(NH)]
    W1 = k.sb("W1", [128, W], FR)
    U1 = k.sb("U1", [128, W])
    ysb = k.sb("ysb", [128, W])
    yc = k.sb("yc", [128, W])
    m4 = k.sb("m4", [128, NH])
    r4 = k.sb("r4", [128, NH])
    ot = [k.sb(f"ot{i}", [128, W]) for i in range(2)]
    B = [k.ps(f"psB{i}", [128, 512]) for i in range(8)]
    bk = lambda i: f'psB{i}'
    v3 = lambda t: t.rearrange("p (h j) -> p h j", h=NH)
    bc4 = lambda t: t.unsqueeze(2).broadcast_to([128, NH, 64])


    S0, S1, C0, C1 = 6, 7, 4, 5

    def tile(i):
        b = i % 2
        pm, lm = pm2[b], lm2[b]
        kpm, klm = f'pm{b}', f'lm{b}'
        At, Bs, Ks, Rt, gv, vr, bon, E1T, Bf, Kf = At2[b], Bs2[b], Ks2[b], Rt2[b], gv2[b], vr2[b], bon2[b], E1T2[b], Bfm2[b], Kfm2[b]
        kAt, kBs, kKs, kRt, kgv, kvr, kbon, kE1T, kBf, kKf = (f'{n_}{b}' for n_ in ('At', 'Bs', 'Ks', 'Rt', 'gv', 'vr', 'bon', 'E1T', 'Bf', 'Kf'))
        rows = slice(i * 128, (i + 1) * 128)
        PK, PPK, LTK, LPK = [], [], [], []
        for q in range(3):
            cq = slice(q * W, (q + 1) * W)
            k.dma('sp', pt[b][:, cq], prkv[q][rows, :], w=[f'pt{q}'])
            PK.append(f'pt{q}')
            if i == 0:
                k.dma('sp', pp[b][1:128, cq], prkv[q][0:127, :], w=[f'pp{q}'])
            else:
                k.dma('sp', pp[b][:, cq], prkv[q][i * 128 - 1:i * 128 + 127, :], w=[f'pp{q}'])
            PPK.append(f'pp{q}')
            nr = pls[q].shape[0]
            k.dma('sp', lt[b][0:nr, q, :], pls[q][:, rows], w=[f'lt{q}'])
            LTK.append(f'lt{q}')
            if i == 0:
                k.dma('sp', lp[b][0:nr, q, 1:128], pls[q][:, 0:127], w=[f'lp{q}'])
            else:
                k.dma('sp', lp[b][0:nr, q, :], pls[q][:, i * 128 - 1:i * 128 + 127], w=[f'lp{q}'])
            LPK.append(f'lp{q}')
        if i == 0:
            k.memset('pool', pp[b][0:1, :], 0.0, ['ppz'])
            k.memset('pool', lp[b][:, :, 0:1], 0.0, ['lpz'])
            PPK.append('ppz')
            LPK.append('lpz')
        k.tt('pool', pm[:], pp[b][:], pt[b][:], ALU.subtract, PPK + PK, [kpm])
        k.tt('pool', pm[:], pm[:], mu1bc[:], ALU.mult, [kpm, 'mu1bc'], [kpm])
        k.tt('pool', pm[:], pm[:], pt[b][:], ALU.add, [kpm] + PK, [kpm])
        r_, k_, v_ = pm[:, 0:W], pm[:, W:2 * W], pm[:, 2 * W:3 * W]
        LK = LTK + LPK
        k.tt('dve', lm[:], lp[b][:], lt[b][:], ALU.subtract, LK, [klm])
        for blk in range(3):
            k.stt(lm[:, blk, :], lm[:, blk, :], muls[:, blk:blk + 1], lt[b][:, blk, :], ALU.mult, ALU.add,
                  [klm, 'muls'] + LK, [klm])
        k.act(lm[0:64, 0, :], lm[0:64, 0, :], AF.Tanh, [klm], [klm])
        k.act(lm[:, 2, :], lm[:, 2, :], AF.Sigmoid, [klm], [klm])
        yield
        k.cp('act', vr[:], v_, [kpm], [kvr])
        k.mm(B[S0][:, 0:W], lm[0:64, 0, :], w2s[:], True, True, [klm, 'w2s'], [bk(S0)])
        k.mm(B[S1][:, 0:W], lm[0:64, 1, :], a2s[:], True, True, [klm, 'a2s'], [bk(S1)])
        k.tt('dve', sw[:], B[S0][:, 0:W], w0bc[:], ALU.add, [bk(S0), VK[0]], ['sw'])
        k.act(sw[:], sw[:], AF.Sigmoid, ['sw'], ['sw'])
        k.tt('dve', av[:], B[S1][:, 0:W], a0bc[:], ALU.add, [bk(S1), VK[1]], ['av'])
        k.act(av[:], av[:], AF.Sigmoid, ['av'], ['av'])
        k.mm(B[S0][:, 0:W], lm[:, 2, :], g2s[:], True, True, [klm, 'g2s'], [bk(S0)])
        k.cp('act', gv[:], B[S0][:, 0:W], [bk(S0)], [kgv])
        yield
        k.tt('pool', kkr[:], k_, kkbc[:], ALU.mult, [kpm, VK[2]], ['kkr'])
        k.tt('pool', sq[:], kkr[:], kkr[:], ALU.mult, ['kkr'], ['sq'])
        k.P.op('dve', lambda e: e.tensor_reduce(out=s4[:], in_=v3(sq[:]), axis=AX.X, op=ALU.add), reads=['sq'], writes=['s4'])
        k.act(s4[:], s4[:], AF.Sqrt, ['s4'], ['s4'])
        k.ts('dve', s4[:], s4[:], 1e-12, None, ALU.max, None, ['s4'], ['s4'])
        k.recip(rn[:], s4[:], ['s4'], ['rn'])
        k.ts('dve', rn[:], rn[:], -1.0, None, ALU.mult, None, ['rn'], ['rn'])
        k.tt('dve', v3(nkk[:]), v3(kkr[:]), bc4(rn[:]), ALU.mult, ['kkr', 'rn'], ['nkk'])
        k.stt(tmp[:], av[:], -1.0, kabc[:], ALU.add, ALU.mult, ['av', VK[3]], ['tmp'])
        k.stt(kmod[:], tmp[:], 1.0, k_, ALU.add, ALU.mult, ['tmp', kpm], ['kmod'])
        k.stt(kka[:], nkk[:], -1.0, av[:], ALU.mult, ALU.mult, ['nkk', 'av'], ['kka'])
        k.tt('pool', tmp[:], r_, kmod[:], ALU.mult, [kpm, 'kmod', 'tmp'], ['tmp'])
        k.tt('pool', tmp[:], tmp[:], rkbc[:], ALU.mult, ['tmp', VK[4]], ['tmp'])
        k.P.op('dve', lambda e: e.tensor_reduce(out=bon[:], in_=v3(tmp[:]), axis=AX.X, op=ALU.add), reads=['tmp'], writes=[kbon])
        k.mm(B[S1][:, 0:W], triw[:, 0, :], sw[:], True, True, ['triw', 'sw'], [bk(S1)])
        k.mm(B[S0][:, 0:W], triw[:, 1, :], sw[:], True, True, ['triw', 'sw'], [bk(S0)])
        k.act(E1[:], B[S1][:, 0:W], AF.Exp, [bk(S1)], ['E1'])
        k.act(E2[:], B[S1][:, 0:W], AF.Exp, [bk(S1)], ['E2'], scale=-1.0)
        k.act(E3[:], B[S0][:, 0:W], AF.Exp, [bk(S0)], ['E3'])
        k.mm(B[S1][:, 0:W], triw[:, 2, :], sw[:], True, True, ['triw', 'sw'], [bk(S1)])
        k.act(E4[:], B[S1][:, 0:W], AF.Exp, [bk(S1)], ['E4'])
        for g in range(2):
            for hl in range(4):
                h = 4 * g + hl
                k.mm(B[S0 + g][0:64, hl * 128:(hl + 1) * 128], sw[:, h * 64:(h + 1) * 64], triw[:, 0, :], True, True,
                     ['sw', 'triw'], [bk(S0 + g)])
        for g in range(2):
            k.act(E1T[:, 4 * g:4 * g + 4, :].rearrange("p a t -> p (a t)"), B[S0 + g][0:64, :], AF.Exp, [bk(S0 + g)], [kE1T])
        yield
        k.tt('dve', At[:], nkk[:], E3[:], ALU.mult, ['nkk', 'E3'], [kAt])
        k.tt('pool', Bs[:], kka[:], E2[:], ALU.mult, ['kka', 'E2'], [kBs])
        k.tt('dve', Ks[:], kmod[:], E2[:], ALU.mult, ['kmod', 'E2'], [kKs])
        k.tt('pool', Rt[:], r_, E1[:], ALU.mult, [kpm, 'E1'], [kRt])
        for c in range(NCK):
            k.stt(Bf[c][:], kka[:], rowm[:, c:c + 1], E4[:], ALU.mult, ALU.mult, ['kka', 'E4', 'rowm'], [kBf])
            k.stt(Kf[c][:], kmod[:], rowm[:, c:c + 1], E4[:], ALU.mult, ALU.mult, ['kmod', 'E4', 'rowm'], [kKf])
        yield
        for g in range(2):
            HS = list(range(4 * g, 4 * g + 4))
            for h in HS:
                hl = h % 4
                cs_ = slice(h * 64, (h + 1) * 64)
                for q, (src, key) in enumerate([(At, kAt), (Bs, kBs), (Ks, kKs), (Rt, kRt)]):
                    k.tr(B[hl][0:64, q * 128:(q + 1) * 128], src[:, cs_], k.identf[:], [key], [bk(hl)])
            for h in HS:
                hl = h % 4
                k.cp('act' if h % 2 else 'dve', FT[h][:].rearrange("p a t -> p (a t)"), B[hl][0:64, :], [bk(hl)], [f'FT{h}'])
            for h in HS:
                hl = h % 4
                AtT, BsT, KsT, RtT = (FT[h][:, q, :] for q in range(4))
                k.mm(B[hl][:, 0:128], BsT, AtT, True, True, [f'FT{h}'], [bk(hl)])
                k.mm(B[hl][:, 128:256], AtT, BsT, True, True, [f'FT{h}'], [bk(hl)])
                k.mm(B[hl][:, 256:384], KsT, AtT, True, True, [f'FT{h}'], [bk(hl)])
            for h in HS:
                hl = h % 4
                k.tt('dve', A5[h][:, 0:384], B[hl][:, 0:384], mask5[:, 0:384], ALU.mult, [bk(hl), 'mask5'], [f'A5_{h}'])
            for h in HS:
                hl = h % 4
                AtT, BsT, KsT, RtT = (FT[h][:, q, :] for q in range(4))
                k.mm(B[hl][:, 0:128], BsT, RtT, True, True, [f'FT{h}'], [bk(hl)])
                k.mm(B[hl][:, 128:256], KsT, RtT, True, True, [f'FT{h}'], [bk(hl)])
            for h in HS:
                hl = h % 4
                k.tt('dve', A5[h][:, 384:640], B[hl][:, 0:256], mask5[:, 384:640], ALU.mult, [bk(hl), 'mask5'], [f'A5b_{h}'])
                k.cp('act', NL[h][:], rd(A5[h][:, 0:256]), [f'A5_{h}'], [f'NL_{h}'])
                k.tt('dve', PQ[h][:, 0:128], rd(A5[h][:, 0:128]), k.identf[:], ALU.add, [f'A5_{h}', 'ident'], [f'PQ_{h}'])
            for lev in range(nlev):
                last = (lev == nlev - 1)
                for h in HS:
                    hl = h % 4
                    N_, L_ = NL[h][:, 0:128], NL[h][:, 128:256]
                    k.mm(B[hl][:, 0:128], L_, N_, True, True, [f'NL_{h}'], [bk(hl)])
                    k.mm(B[hl][:, 128:256], N_, L_, True, True, [f'NL_{h}'], [bk(hl)])
                for h in HS:
                    hl = h % 4
                    k.cp('act', NL[h][:], B[hl][:, 0:256], [bk(hl)], [f'NL_{h}'])
                for h in HS:
                    hl = h % 4
                    k.mm(B[hl][:, 256:384], NL[h][:, 128:256], PQ[h][:, 0:128], True, True, [f'NL_{h}', f'PQ_{h}'], [bk(hl)])
                for h in HS:
                    hl = h % 4
                    k.tt('dve', PQ[h][:, 0:128], B[hl][:, 256:384], rd(PQ[h][:, 0:128]), ALU.add, [bk(hl), f'PQ_{h}'], [f'PQ_{h}'])
            yield
        for h in range(NH):
            k.mm(B[C0][:, h * 64:(h + 1) * 64], A5[h][:, 256:384], vr[:, h * 64:(h + 1) * 64], True, True, [f'A5_{h}', kvr], [bk(C0)])
        k.cp('act', W1[:], B[C0][:, 0:W], [bk(C0)], ['W1'])
        for h in range(NH):
            k.mm(B[C1][:, h * 64:(h + 1) * 64], PQ[h][:, 0:128], W1[:, h * 64:(h + 1) * 64], True, True,
                 [f'PQ_{h}', 'W1'], [bk(C1)])
        k.cp('act', U1[:], B[C1][:, 0:W], [bk(C1)], ['U1'])
        for c in range(NCK):
            cr = slice(c * CH, (c + 1) * CH)
            for h in range(NH):
                k.mm(B[C0][cr, h * 64:(h + 1) * 64], lhc(FT[h][:, 0, cr]), ST[h][:], True, True, [f'FT{h}', f'ST{h}'], [bk(C0)])
            k.cp('act', P1s[cr, :], B[C0][cr, 0:W], [bk(C0)], ['P1s'])
            for h in range(NH):
                k.mm(B[C0][cr, h * 64:(h + 1) * 64], lhc(PQ[h][:, cr]), P1s[:, h * 64:(h + 1) * 64], True, True,
                     [f'PQ_{h}', 'P1s'], [bk(C0)])
            k.tt('dve', Us[cr, :], B[C0][cr, 0:W], U1[cr, :], ALU.add, [bk(C0), 'U1'], ['Us'])
            for h in range(NH):
                hc_ = slice(h * 64, (h + 1) * 64)
                vh = vr[:, hc_] if frc else rd(vr[:, hc_])
                k.mm(B[C0][cr, hc_], lhc(FT[h][:, 3, cr]), ST[h][:], True, False, [f'FT{h}', f'ST{h}'], [bk(C0)])
                k.mm(B[C0][cr, hc_], lhc(A5[h][:, 384:512][:, cr]), Us[:, hc_], False, False, [f'A5b_{h}', 'Us'], [bk(C0)])
                k.mm(B[C0][cr, hc_], lhc(A5[h][:, 512:640][:, cr]), vh, False, True, [f'A5b_{h}', kvr], [bk(C0)])
            for h in range(NH):
                hc_ = slice(h * 64, (h + 1) * 64)
                k.mm(B[C1][0:64, hc_], Bf[c][:, hc_], rdc(Us[:, hc_]), True, False, [kBf, 'Us'], [bk(C1)])
                k.mm(B[C1][0:64, hc_], Kf[c][:, hc_], rd(vr[:, hc_]), False, True, [kKf, kvr], [bk(C1)])
            for h in range(NH):
                hc_ = slice(h * 64, (h + 1) * 64)
                k.stt(ST[h][:], rdc(ST[h][:]), E1T[:, h, (c + 1) * CH - 1:(c + 1) * CH], B[C1][0:64, hc_], ALU.mult, ALU.add,
                      [f'ST{h}', kE1T, bk(C1)], [f'ST{h}'])
        k.cp('act', ysb[:], B[C0][:, 0:W], [bk(C0)], ['ysb'])
        k.P.op('dve', lambda e: e.tensor_reduce(out=m4[:], in_=v3(ysb[:]), axis=AX.X, op=ALU.add), reads=['ysb'], writes=['m4'])
        k.ts('dve', m4[:], m4[:], -1.0 / 64.0, None, ALU.mult, None, ['m4'], ['m4'])
        k.tt('dve', v3(yc[:]), v3(ysb[:]), bc4(m4[:]), ALU.add, ['ysb', 'm4'], ['yc'])
        k.tt('pool', sqp[:], yc[:], yc[:], ALU.mult, ['yc'], ['sqp'])
        k.P.op('dve', lambda e: e.tensor_reduce(out=r4[:], in_=v3(sqp[:]), axis=AX.X, op=ALU.add), reads=['sqp'], writes=['r4'])
        k.ts('dve', r4[:], r4[:], 1.0 / 64.0, GN_EPS, ALU.mult, ALU.add, ['r4'], ['r4'])
        k.act(r4[:], r4[:], AF.Sqrt, ['r4'], ['r4'])
        k.recip(r4[:], r4[:], ['r4'], ['r4'])
        k.tt('dve', v3(yc[:]), v3(yc[:]), bc4(r4[:]), ALU.mult, ['yc', 'r4'], ['yc'])
        k.tt('pool', yc[:], yc[:], lngbc[:], ALU.mult, ['yc', VK[5]], ['yc'])
        k.tt('pool', yc[:], yc[:], lnbbc[:], ALU.add, ['yc', VK[6]], ['yc'])
        k.tt('dve', v3(tmpp[:]), v3(rd(vr[:])), bc4(bon[:]), ALU.mult, [kvr, kbon], ['tmpp'])
        k.tt('pool', yc[:], yc[:], tmpp[:], ALU.add, ['yc', 'tmpp'], ['yc'])
        k.tt('dve', ot[b][:], yc[:], gv[:], ALU.mult, ['yc', kgv], [f'ot{b}'])
        k.dma('pool', oc[rows, :], ot[b][:], r=[f'ot{b}'], final=True)

    gens = {}

    def adv(j):
        if 0 <= j < NT:
            try:
                next(gens[j])
            except StopIteration:
                pass

    for step in range(NT + 2):
        if step < NT:
            gens[step] = tile(step)
            adv(step)
        for r_i in range(3):
            adv(step - 1)
            adv(step - 2)
    return k.finish()


def rwkv_consts(CH=64):
    c = -math.exp(-0.5)
    blk = np.kron(np.eye(128 // CH), np.ones((CH, CH)))
    s_idx = np.arange(128)[:, None]
    t_idx = np.arange(128)[None, :]
    triw = np.stack([c * blk * (s_idx <= t_idx), c * blk * (s_idx < t_idx), c * blk * (s_idx > t_idx)]).astype(np.float32)
    lt_, le_, gt_ = blk * (s_idx < t_idx), blk * (s_idx <= t_idx), blk * (t_idx < s_idx)
    mask5 = np.concatenate([lt_, gt_, lt_, le_, le_], 1).astype(np.float32)
    rowm = np.stack([(np.arange(128) < 64), (np.arange(128) >= 64)], 1).astype(np.float32) if CH == 64 else np.ones((128, 2), np.float32)
    return dict(ident=np.eye(128, dtype=np.float32), triw=triw, mask5=mask5, rowm=rowm)


def rwkv_host_inputs(s, p_rwkv, prm, NH=4, CH=64):
    L = p_rwkv.shape[0]
    cs = slice(64 * NH * s, 64 * NH * (s + 1))
    r_, w1, k_, v_, a1, g1 = np.split(p_rwkv, np.cumsum([512, 64, 512, 512, 64])[:5], axis=-1)
    mu = prm['rwkv_mu']
    mur, muw1, muk, muv, mua1, mug1 = np.split(mu, np.cumsum([512, 64, 512, 512, 64])[:5])
    zm = np.zeros(64, np.float32)
    mul = np.concatenate([muw1, zm, mua1, zm, mug1]).reshape(3, 128).T
    vecs = np.stack([prm['rwkv_w0'][cs], prm['rwkv_a0'][cs], prm['rwkv_k_k'][cs], prm['rwkv_k_a'][cs],
                     prm['rwkv_r_k'].reshape(-1)[cs], prm['rwkv_ln_gain'][cs], prm['rwkv_ln_bias'][cs]])
    c_ = np.ascontiguousarray
    d = dict(pr=c_(r_[:, cs]), pk=c_(k_[:, cs]), pv=c_(v_[:, cs]),
             mu1=c_(np.concatenate([mur[cs], muk[cs], muv[cs]])),
             plw=c_(w1.T), pla=c_(a1.T), plg=c_(g1.T), mul=c_(mul),
             w2=c_(prm['rwkv_w2'][:, cs]), a2=c_(prm['rwkv_a2'][:, cs]),
             g2=c_(prm['rwkv_g2'][:, cs]), vecs=c_(vecs))
    d.update(rwkv_consts(CH))
    return d


FM0 = [(0, 128, 0), (128, 128, 128), (256, 128, 256), (384, 128, 384), (1536, 16, 512)] + \
      [(1552 + j * 128, 128, 528 + j * 128) for j in range(4)]
NF0 = 1040
FM1 = [(512, 64, 0), (1600, 64, 64), (1664, 128, 128)] + [(1792 + j * 128, 128, 256 + j * 128) for j in range(8)]
NF1 = 1280


def host_params(inp):
    c_ = lambda a: np.ascontiguousarray(np.asarray(a), dtype=np.float32)
    P = {}
    P['ident'] = np.eye(128, dtype=np.float32)
    P['triu'] = np.triu(np.ones((128, 128), np.float32))
    P['trigt'] = np.tril(np.ones((128, 128), np.float32), -1)
    for l in range(2):
        for j in range(7):
            P[f'g{l}_{j}'] = c_(inp['norm_gain'][l][j])
        for nm in ('xa_wq', 'xa_wk', 'xa_wv', 'xa_wo', 'mlp_w1', 'mlp_w2'):
            P[f'{nm}{l}'] = c_(inp[nm][l])
    P['w_in0'] = c_(inp['ab_w_in'][0])
    P['w_in1'] = c_(inp['cd_w_in'][0])
    P['w_out0'] = c_(inp['ab_w_out'][0])
    P['w_out1'] = c_(inp['cd_w_out'][0])
    P['wglu'] = c_(inp['s5_w_glu'][0])
    P['bglu'] = c_(inp['s5_b_glu'][0])
    prm0 = {k_: np.asarray(inp[k_][0]) for k_ in inp if k_.startswith('s5_') or k_.startswith('gla_')}
    prm1 = {k_: np.asarray(inp[k_][0]) for k_ in inp if k_.startswith('rwkv_') or k_.startswith('lru_')}
    for s in range(2):
        cs = slice(s * 128, (s + 1) * 128)
        P[f'gla_w2_{s}'] = c_(prm0['gla_w_decay2'][:, cs])
        P[f'gla_bd_{s}'] = c_(prm0['gla_b_decay'][None, cs])
        P[f'gla_gn_{s}'] = c_(prm0['gla_norm_gain'][2 * s:2 * s + 2].reshape(256))
        d = s5_host_inputs(s, np.zeros((2, 512), np.float32), prm0)
        for nm in ('lam_re', 'lam_im', 'lstep', 'Bre', 'Bim', 'Cre', 'Cim', 'dsk'):
            P[f's5_{nm}_{s}'] = c_(d[nm])
        P['iota_p'] = c_(d['iota_p'])
        P['iota_f'] = c_(d['iota_f'])
        if s == 0:
            d = rwkv_host_inputs(0, np.zeros((2, 1792), np.float32), prm1, 8, 64)
            for nm in ('mu1', 'mul', 'w2', 'a2', 'g2', 'vecs'):
                P[f'rw_{nm}'] = c_(d[nm])
            for nm in ('triw', 'mask5', 'rowm'):
                P[f'rw_{nm}'] = c_(d[nm])
        d = lru_host_inputs(s, np.zeros((2, 512), np.float32), np.zeros((2, 512), np.float32), prm1)
        for nm in ('cw', 'cb', 'Wa', 'Wx', 'ba', 'bx', 'lam'):
            P[f'lru_{nm}_{s}'] = c_(d[nm])
    return P


def build_fused(P, L):
    k = K(fused=True)
    X = {nm: k.xin(nm, a.shape) for nm, a in P.items()}
    x = k.xin('x', [L, D])
    mem = k.xin('mem', [256, D])
    out = k.xout('out', [L, D])
    proj0 = k.scratch('proj0', [L, 2064])
    PT0 = k.scratch('PT0', [NF0, L])
    proj1 = k.scratch('proj1', [L, 2816])
    PT1 = k.scratch('PT1', [NF1, L])
    o = k.scratch('o', [L, D])
    odT = k.scratch('odT', [512, L])
    h1 = k.scratch('h1', [L, D])
    h2 = k.scratch('h2', [L, D])
    h3 = k.scratch('h3', [L, D])

    def cblock(l, hin, hout, glu, ob_fm):
        io = dict(oa=o[:, 0:512], hin=hin, wout=X[f'w_out{l}'], g1=X[f'g{l}_1'], ident=X['ident'], hout=h1)
        if ob_fm:
            io['obT'] = odT
        else:
            io['ob'] = o[:, 512:1024]
        if glu:
            io.update(wglu=X['wglu'], bglu=X['bglu'])
        k.begin_phase(f'C1_{l}', io)
        build_C1(L, glu, k=k, ob_fm=ob_fm)
        k.begin_phase(f'C2_{l}', dict(hin=h1, mem=mem, wq=X[f'xa_wq{l}'], wk=X[f'xa_wk{l}'], wv=X[f'xa_wv{l}'], wo=X[f'xa_wo{l}'],
                                      g2=X[f'g{l}_2'], g3=X[f'g{l}_3'], g6=X[f'g{l}_6'], ident=X['ident'], hout=h2))
        build_C2(L, k=k)
        k.begin_phase(f'C3_{l}', dict(hin=h2, w1=X[f'mlp_w1{l}'], w2=X[f'mlp_w2{l}'], g4=X[f'g{l}_4'], g5=X[f'g{l}_5'],
                                      ident=X['ident'], hout=hout))
        build_C3(L, k=k)

    k.begin_phase('A0', dict(x=x, gain=X['g0_0'], W=X['w_in0'], ident=X['ident'], out=proj0, outT=PT0))
    build_A2(L, 2064, FM0, NF0, k=k)
    streams = []
    for s in range(2):
        io = dict(qT=PT0[s * 128:(s + 1) * 128, :], kT=PT0[256 + s * 128:256 + (s + 1) * 128, :],
                  ktok=proj0[:, 256 + s * 128:256 + (s + 1) * 128], v=proj0[:, 512 + s * 256:512 + (s + 1) * 256],
                  gate=proj0[:, 1024 + s * 256:1024 + (s + 1) * 256], dlrT=PT0[512:528, :],
                  w2=X[f'gla_w2_{s}'], bdec=X[f'gla_bd_{s}'], gn=X[f'gla_gn_{s}'], triu=X['triu'],
                  trigt=X['trigt'], oa=o[:, s * 256:(s + 1) * 256])
        streams.append((f'g{s}_', io, lambda kk: gen_GLA(L, kk)))
    k.begin_phase('GLA', {})
    run_streams(k, streams)
    k.finish()
    for s in range(2):
        io = dict(uT=PT0[528 + s * 256:528 + (s + 1) * 256, :], u=proj0[:, 1552 + s * 256:1552 + (s + 1) * 256],
                  triu=X['triu'], iota_p=X['iota_p'], iota_f=X['iota_f'], y=o[:, 512 + s * 256:512 + (s + 1) * 256])
        for nm in ('lam_re', 'lam_im', 'lstep', 'Bre', 'Bim', 'Cre', 'Cim', 'dsk'):
            io[nm] = X[f's5_{nm}_{s}']
        k.begin_phase(f'S5{s}', io)
        build_S5(L, k=k)
    cblock(0, x, h3, True, False)
    k.begin_phase('A1', dict(x=h3, gain=X['g1_0'], W=X['w_in1'], ident=X['ident'], out=proj1, outT=PT1))
    build_A2(L, 2816, FM1, NF1, k=k)
    io = dict(pr=proj1[:, 0:512], pk=proj1[:, 576:1088], pv=proj1[:, 1088:1600], plw=PT1[0:64, :], pla=PT1[64:128, :],
              plg=PT1[128:256, :], ident=X['ident'], triw=X['rw_triw'], mask5=X['rw_mask5'], rowm=X['rw_rowm'], oc=o[:, 0:512])
    for nm in ('mu1', 'mul', 'w2', 'a2', 'g2', 'vecs'):
        io[nm] = X[f'rw_{nm}']
    k.begin_phase('RW', io)
    build_RWKVP(L, k=k, CH=64)
    streams = []
    for s in range(2):
        io = dict(xbT=PT1[256 + s * 256:256 + (s + 1) * 256, :], gateT=PT1[768 + s * 256:768 + (s + 1) * 256, :],
                  odT=odT[s * 256:(s + 1) * 256, :])
        for nm in ('cw', 'cb', 'Wa', 'Wx', 'ba', 'bx', 'lam'):
            io[nm] = X[f'lru_{nm}_{s}']
        streams.append((f'l{s}_', io, lambda kk: gen_LRU(L, kk)))
    k.begin_phase('LRU', {})
    run_streams(k, streams)
    k.finish()
    cblock(1, h3, out, False, True)
    return k.finish_program()


BATCH, SEQ = 4, 4096
_CACHE = {}


def kernel(**inp):
    inp = {k_: np.asarray(v_) for k_, v_ in inp.items()}
    P = host_params(inp)
    if 'nc' not in _CACHE:
        _CACHE['nc'] = build_fused(P, SEQ)
    nc = _CACHE['nc']
    maps = []
    for b in range(BATCH):
        m = dict(P)
        m['x'] = np.ascontiguousarray(inp['x'][b], dtype=np.float32)
        m['mem'] = np.ascontiguousarray(inp['mem'][b], dtype=np.float32)
        maps.append(m)
    res = run_bass_kernel_spmd(nc, maps, core_ids=list(range(BATCH))).results
    return np.ascontiguousarray(np.stack([res[b]['out'] for b in range(BATCH)]).astype(np.float32))
```

```python
import os
import math
from contextlib import ExitStack


import numpy as np
import concourse.bass as bass
import concourse.mybir as mybir
from concourse.bass_utils import run_bass_kernel_spmd

F32 = mybir.dt.float32
BF16 = mybir.dt.bfloat16
I32 = mybir.dt.int32
AF = mybir.ActivationFunctionType
ALU = mybir.AluOpType
AX = mybir.AxisListType

ENGS = ['pe', 'act', 'dve', 'pool', 'sp']
NDMA_SLOTS = 8
SAME_ENGINE_SYNC = os.environ.get("NOSELF", "0") != "1"


class Prog:
    def __init__(self, nc):
        self.nc = nc
        self.ops = {e: [] for e in ENGS}
        self.cnt = {e: 0 for e in ENGS}
        self.last_w = {}
        self.readers = {}
        self.seen = {e: {} for e in ENGS}
        self.dma_n = {e: 0 for e in ENGS}
        self.dma_tok = {e: [None] * NDMA_SLOTS for e in ENGS}
        self.final_tokens = []
        from contextlib import ExitStack
        self.sem_stack = ExitStack()
        self.sems = {}
        for e in ['pe', 'act', 'dve', 'pool']:
            self.sems[('c', e)] = self.sem_stack.enter_context(nc.semaphore("s_c_" + e))
        for q in ['sp', 'pool']:
            for sl in range(NDMA_SLOTS):
                self.sems[('d', q, sl)] = self.sem_stack.enter_context(nc.semaphore(f"s_d_{q}_{sl}"))

    def barrier(self):
        toks = []
        for e in ['pe', 'act', 'dve', 'pool']:
            if self.cnt[e] > 0:
                toks.append((('c', e), self.cnt[e]))
        for q in ENGS:
            for t in self.dma_tok[q]:
                if t is not None:
                    toks.append(t)
        for e in ENGS:
            waits = []
            for (sem, val) in toks:
                if sem == ('c', e):
                    continue
                if self.seen[e].get(sem, 0) >= val:
                    continue
                waits.append((sem, val))
                self.seen[e][sem] = val
            if waits:
                self.ops[e].append((waits, None, None))
        self.last_w = {}
        self.readers = {}

    def _deps(self, eng, reads, writes):
        toks = []
        for r in reads:
            t = self.last_w.get(r)
            if t is not None:
                toks.append(t)
        for w in writes:
            t = self.last_w.get(w)
            if t is not None:
                toks.append(t)
            toks.extend(self.readers.get(w, []))
        need = {}
        for (sem, val) in toks:
            if not SAME_ENGINE_SYNC and sem == ('c', eng):
                continue
            if sem == ('c', 'pe') and eng == 'pe':
                continue
            if self.seen[eng].get(sem, 0) >= val:
                continue
            if need.get(sem, 0) < val:
                need[sem] = val
        for sem, val in need.items():
            self.seen[eng][sem] = val
        return list(need.items())

    def _commit(self, tok, reads, writes):
        for w in writes:
            self.last_w[w] = tok
            self.readers[w] = []
        for r in reads:
            if r in writes:
                continue
            self.readers.setdefault(r, []).append(tok)

    def op(self, eng, fn, reads=(), writes=()):
        self.nrec = getattr(self, 'nrec', 0) + 1
        if self.nrec > int(os.environ.get("MAXOPS", "100000000")):
            return None
        kp = getattr(self, 'key_prefix', '')
        reads = [r if r.startswith('ps') else kp + r for r in reads]
        writes = [w if w.startswith('ps') else kp + w for w in writes]
        pk = getattr(self, 'ps_prefix', '')
        reads = [('ps' + pk + r[2:]) if r.startswith('ps') else r for r in reads]
        writes = [('ps' + pk + w[2:]) if w.startswith('ps') else w for w in writes]
        writes = list(writes) + [r for r in reads if r.startswith('ps') and r not in writes]
        waits = self._deps(eng, reads, writes)
        self.cnt[eng] += 1
        tok = (('c', eng), self.cnt[eng])
        self.ops[eng].append((waits, fn, tok))
        self._commit(tok, reads, writes)
        return tok

    def dma(self, q, out, in_, reads=(), writes=(), final=False, **kw):
        self.nrec = getattr(self, 'nrec', 0) + 1
        if self.nrec > int(os.environ.get("MAXOPS", "100000000")):
            return None
        kp = getattr(self, 'key_prefix', '')
        reads = [kp + r for r in reads]
        writes = [kp + w for w in writes]
        waits = self._deps(q, reads, writes)
        n = self.dma_n[q]
        slot = n % NDMA_SLOTS
        prev = self.dma_tok[q][slot]
        if prev is not None and self.seen[q].get(prev[0], 0) < prev[1]:
            waits.append(prev)
            self.seen[q][prev[0]] = prev[1]
        tok = (('d', q, slot), 16 * (n // NDMA_SLOTS + 1))
        self.dma_n[q] += 1
        self.dma_tok[q][slot] = tok

        def fn(e, out=out, in_=in_, kw=kw):
            return e.dma_start(out=out, in_=in_, **kw)
        self.ops[q].append((waits, fn, tok))
        self._commit(tok, reads, writes)
        if final:
            self.final_tokens.append(tok)
        return tok

    def emit(self, last=True):
        nc = self.nc
        sems = self.sems
        with nc.Block() as block:
            final = list(self.final_tokens) if last else []

            def run(e, name):
                for waits, fn, tok in self.ops[name]:
                    for (s, v) in waits:
                        e.wait_ge(sems[s], v)
                    if fn is None:
                        continue
                    inst = fn(e)
                    inc = 16 if tok[0][0] == 'd' else 1
                    inst.then_inc(sems[tok[0]], inc)
                if name == 'sp':
                    for (s, v) in final:
                        e.wait_ge(sems[s], v)
                self.ops[name] = []

            @block.tensor
            def _(e):
                run(e, 'pe')

            @block.scalar
            def _(e):
                run(e, 'act')

            @block.vector
            def _(e):
                run(e, 'dve')

            @block.gpsimd
            def _(e):
                run(e, 'pool')

            @block.sync
            def _(e):
                run(e, 'sp')
        if last:
            self.sem_stack.close()


D = 1024
KC = 8
EPS = 1e-6


class K:
    def __init__(self, fused=False):
        self.nc = bass.Bass("TRN2", target_bir_lowering=False)
        self.st = ExitStack()
        self.P = Prog(self.nc)
        self.n = 0
        self.fused = fused
        self.io = {}
        self.pfx = ""

    def begin_phase(self, name, io):
        self.pfx = name + "_"
        self.io = io
        self.st = ExitStack()
        for a in ('wstage', 'rr_cache', 'identf', 'identb'):
            if hasattr(self, a):
                delattr(self, a)

    def scratch(self, name, shape, dt=F32):
        return self.nc.dram_tensor(name, list(shape), dt, kind="Internal").ap()

    def xin(self, name, arr_shape, dt=F32):
        return self.nc.dram_tensor(name, list(arr_shape), dt, kind="ExternalInput").ap()

    def xout(self, name, arr_shape, dt=F32):
        return self.nc.dram_tensor(name, list(arr_shape), dt, kind="ExternalOutput").ap()

    def din(self, name, shape, dt=F32):
        if self.fused:
            ap = self.io[name]
            assert list(ap.shape) == list(shape), (name, ap.shape, shape)
            return ap
        return self.nc.dram_tensor(name, list(shape), dt, kind="ExternalInput").ap()

    def dout(self, name, shape, dt=F32):
        if self.fused:
            ap = self.io[name]
            assert list(ap.shape) == list(shape), (name, ap.shape, shape)
            return ap
        return self.nc.dram_tensor(name, list(shape), dt, kind="ExternalOutput").ap()

    def sb(self, name, shape, dt=F32):
        pers = getattr(self, 'persist', None)
        if pers is not None and (self.pfx + name) in pers:
            return pers[self.pfx + name]
        return self.st.enter_context(self.nc.sbuf_tensor(self.pfx + name, list(shape), dt))

    def push_scope(self, persistent):
        self.persist = getattr(self, 'persist', None) or {}
        for (name, shape, dt) in persistent:
            self.persist[self.pfx + name] = self.st.enter_context(self.nc.sbuf_tensor(self.pfx + name, list(shape), dt))
        self._st_saved = self.st
        self.st = ExitStack()

    def pop_scope(self):
        self.P.barrier()
        self.P.emit(last=False)
        self.st.close()
        self.st = self._st_saved

    def ps(self, name, shape, dt=F32):
        return self.st.enter_context(self.nc.psum_tensor(self.pfx + name, list(shape), dt))

    def finish(self, last=True):
        if self.fused:
            self.P.barrier()
            self.P.emit(last=False)
            self.st.close()
            return None
        self.P.emit()
        self.st.close()
        return self.nc

    def finish_program(self):
        self.P.emit(last=True)
        return self.nc

    def mm(self, out, lhsT, rhs, start, stop, r, w):
        self.P.op('pe', lambda e: e.matmul(out, lhsT=lhsT, rhs=rhs, start=start, stop=stop), reads=r, writes=w)

    def tr(self, out, in_, ident, r, w):
        self.P.op('pe', lambda e: e.transpose(out=out, in_=in_, identity=ident), reads=list(r) + ['ident'], writes=w)

    def act(self, out, in_, func, r, w, **kw):
        self.P.op('act', lambda e: e.activation(out=out, in_=in_, func=func, **kw), reads=r, writes=w)

    def tt(self, eng, out, in0, in1, op, r, w):
        self.P.op(eng, lambda e: e.tensor_tensor(out=out, in0=in0, in1=in1, op=op), reads=r, writes=w)

    def ts(self, eng, out, in0, s1, s2, op0, op1, r, w):
        if op1 is None:
            self.P.op(eng, lambda e: e.tensor_scalar(out=out, in0=in0, scalar1=s1, scalar2=None, op0=op0), reads=r, writes=w)
        else:
            self.P.op(eng, lambda e: e.tensor_scalar(out=out, in0=in0, scalar1=s1, scalar2=s2, op0=op0, op1=op1), reads=r, writes=w)

    def stt(self, out, in0, scalar, in1, op0, op1, r, w):
        self.P.op('dve', lambda e: e.scalar_tensor_tensor(out=out, in0=in0, scalar=scalar, in1=in1, op0=op0, op1=op1),
                  reads=r, writes=w)

    def cp(self, eng, out, in_, r, w):
        if eng == 'act':
            self.P.op('act', lambda e: e.copy(out=out, in_=in_), reads=r, writes=w)
        else:
            self.P.op(eng, lambda e: e.tensor_copy(out=out, in_=in_), reads=r, writes=w)

    def recip(self, out, in_, r, w):
        self.P.op('dve', lambda e: e.reciprocal(out=out, in_=in_), reads=r, writes=w)

    def memset(self, eng, ap, val, w):
        self.P.op(eng, lambda e: e.memset(ap, val), reads=[], writes=w)

    def dma(self, q, out, in_, r=(), w=(), final=False, **kw):
        self.P.dma(q, out, in_, reads=r, writes=w, final=final, **kw)

    def consts(self, ident_d):
        self.identf = self.sb("identf", [128, 128], F32)
        self.identb = self.sb("identb", [128, 128], BF16)
        self.dma('sp', self.identf[:], ident_d, w=['ident'])
        self.cp('dve', self.identb[:], self.identf[:], ['ident'], ['ident'])

    def gain_cols(self, name, g_d):
        t = self.sb(name, [128, KC], F32)
        self.dma('sp', t[:], g_d.rearrange("(kc p) -> p kc", p=128), w=[name], allow_slow_non_contiguous=True)
        return t

    def bcast_row(self, name, vec_d, n):
        t = self.sb(name, [128, n], F32)
        self.dma('sp', t[:], vec_d.partition_broadcast(128), w=[name])
        return t

    def load_weight(self, name, w_d, kchunks, ncols, gcol=None, gkey=None, stage_cols=2048, q='sp'):
        wb = self.sb(name, [128, kchunks, ncols], BF16)
        if not hasattr(self, 'wstage'):
            self.wstage = [self.sb(f"wstage{i}", [128, stage_cols], F32) for i in range(2)]
            self.wstage_n = 0
            self.wstage_cols = stage_cols
        sc = self.wstage_cols
        wv = w_d.rearrange("(kc p) n -> p kc n", p=128)
        for kc in range(kchunks):
            for c0 in range(0, ncols, sc):
                cw = min(sc, ncols - c0)
                b = self.wstage_n % 2
                self.wstage_n += 1
                stg = self.wstage[b]
                self.dma(q, stg[:, 0:cw], wv[:, kc, c0:c0 + cw], w=[f'wstage{b}'])
                eng = 'act' if (kc % 2 == 0) else 'dve'
                if gcol is not None:
                    if eng == 'act':
                        self.act(wb[:, kc, c0:c0 + cw], stg[:, 0:cw], AF.Copy, [f'wstage{b}', gkey], [f'{name}{kc}'],
                                 scale=gcol[:, kc:kc + 1])
                    else:
                        self.ts('dve', wb[:, kc, c0:c0 + cw], stg[:, 0:cw], gcol[:, kc:kc + 1], None, ALU.mult, None,
                                [f'wstage{b}', gkey], [f'{name}{kc}'])
                else:
                    self.cp(eng, wb[:, kc, c0:c0 + cw], stg[:, 0:cw], [f'wstage{b}'], [f'{name}{kc}'])
        return wb

    def rstd_of(self, x_ap, xkey, ss, rstd, junk, key, ncols=D):
        self.act(junk, x_ap, AF.Square, [xkey], ['junk', key + 'ss'], accum_out=ss)
        self.ts('dve', rstd, ss, 1.0 / ncols, EPS, ALU.mult, ALU.add, [key + 'ss'], [key])
        self.act(rstd, rstd, AF.Sqrt, [key], [key])
        self.recip(rstd, rstd, [key], [key])


def pipeline(make_gen, n):
    active = []
    for i in range(n):
        for g in list(active):
            try:
                next(g)
            except StopIteration:
                active.remove(g)
        g = make_gen(i)
        active.append(g)
        try:
            next(g)
        except StopIteration:
            active.remove(g)
    while active:
        for g in list(active):
            try:
                next(g)
            except StopIteration:
                active.remove(g)


def pipeline_gen(make_gen, n):
    active = []
    for i in range(n):
        for g in list(active):
            try:
                next(g)
            except StopIteration:
                active.remove(g)
        g = make_gen(i)
        active.append(g)
        try:
            next(g)
        except StopIteration:
            active.remove(g)
        yield
    while active:
        for g in list(active):
            try:
                next(g)
            except StopIteration:
                active.remove(g)
        yield


def run_streams(k, streams):
    base_pfx = k.pfx
    gens = []
    for (pf, io, gf) in streams:
        gens.append([pf, io, None, gf])
    active = list(gens)
    while active:
        for st in list(active):
            pf, io, g, gf = st
            k.pfx = base_pfx + pf
            k.P.key_prefix = pf
            k.P.ps_prefix = pf
            k.io = io
            try:
                if g is None:
                    st[2] = gf(k)
                    g = st[2]
                next(g)
            except StopIteration:
                active.remove(st)
    k.pfx = base_pfx
    k.P.key_prefix = ''
    k.P.ps_prefix = ''


GELU_C = 1.5957691216057308


def norm_T(k, xt, xkey, xn, xnkey, xT_dst, xTkey, psT, psTkey, ss, rstd, junk, key, evac_eng='act'):
    k.rstd_of(xt, xkey, ss, rstd, junk, key)
    k.ts('dve', xn, xt, rstd, None, ALU.mult, None, [xkey, key], [xnkey])
    for kc in range(KC):
        k.tr(psT[:, kc * 128:(kc + 1) * 128], xn[:, kc * 128:(kc + 1) * 128], k.identb[:], [xnkey], [psTkey])
    k.cp(evac_eng, xT_dst, psT[:].rearrange("p (k t) -> p k t", k=KC), [psTkey], [xTkey])


def post_norm_res(k, ps2, pskeys, ht, hkey, gbc, gkey, tmp2, tmpkeys, ss2, rstd, junk, key):
    for j in range(2):
        k.act(junk[:, 0:512], ps2[j], AF.Square, [pskeys[j]], ['junk', key + f'ss{j}'], accum_out=ss2[:, j:j + 1])
    k.tt('dve', ss2[:, 0:1], ss2[:, 0:1], ss2[:, 1:2], ALU.add, [key + 'ss0', key + 'ss1'], [key + 'ss0'])
    k.ts('dve', rstd, ss2[:, 0:1], 1.0 / D, EPS, ALU.mult, ALU.add, [key + 'ss0'], [key])
    k.act(rstd, rstd, AF.Sqrt, [key], [key])
    k.recip(rstd, rstd, [key], [key])
    for j in range(2):
        sl = slice(j * 512, (j + 1) * 512)
        k.stt(tmp2[j], ps2[j], rstd, gbc[:, sl], ALU.mult, ALU.mult, [pskeys[j], key, gkey], [tmpkeys[j]])
        k.tt('pool', ht[:, sl], ht[:, sl], tmp2[j], ALU.add, [tmpkeys[j], hkey], [hkey])


def build_C1(NTOK, glu, k=None, ob_fm=False):
    k = k or K()
    NT = NTOK // 128
    NB = 3
    oa = k.din("oa", [NTOK, 512])
    if ob_fm:
        obT = k.din("obT", [512, NTOK])
    else:
        ob = k.din("ob", [NTOK, 512])
    hin = k.din("hin", [NTOK, D])
    wout = k.din("wout", [D, D])
    g1 = k.din("g1", [D])
    ident_d = k.din("ident", [128, 128])
    if glu:
        wglu = k.din("wglu", [512, 512])
        bglu = k.din("bglu", [512])
    hout = k.dout("hout", [NTOK, D])
    k.consts(ident_d)
    g1bc = k.bcast_row("g1bc", g1, D)
    Wout = k.load_weight("Wout", wout, KC, D, stage_cols=1024)
    if glu:
        Wglu = k.load_weight("Wglu", wglu, 4, 512)
        bgbc = k.bcast_row("bgbc", bglu, 512)
    R = range(NB)
    oc = [k.sb(f"oc{i}", [128, D]) for i in R]
    ocb = [k.sb(f"ocb{i}", [128, D], BF16) for i in R]
    oT = [k.sb(f"oT{i}", [128, KC, 128], BF16) for i in R]
    ht = [k.sb(f"ht{i}", [128, D]) for i in R]
    if ob_fm:
        obt = [k.sb(f"obt{i}", [128, 4, 128]) for i in R]
    tmp = [[k.sb(f"tmp{i}_{j}", [128, 512]) for j in range(2)] for i in range(2)]
    junk = k.sb("junk", [128, D], BF16)
    ss2 = [k.sb(f"ss2{i}", [128, 2]) for i in R]
    rstd = [k.sb(f"rstd{i}", [128, 1]) for i in R]
    if glu:
        yb = [k.sb(f"yb{i}", [128, 512], BF16) for i in R]
        yT = [k.sb(f"yT{i}", [128, 4, 128], BF16) for i in R]
        zs = [k.sb(f"zs{i}", [128, 512]) for i in R]
        t1 = [k.sb(f"t1{i}", [128, 512]) for i in R]
        t2 = [k.sb(f"t2{i}", [128, 512]) for i in R]
    psT = [k.ps(f"psT{i}", [128, D], BF16) for i in range(2)]
    psM = [k.ps(f"psM{i}", [128, 512]) for i in range(4)]
    if glu:
        psG = [k.ps(f"psG{i}", [128, 512]) for i in range(2)]

    def tile(i):
        b = i % NB
        b2 = i % 2
        rows = slice(i * 128, (i + 1) * 128)
        k.dma('sp', oc[b][:, 0:512], oa[rows, :], w=[f'oA{b}'])
        if ob_fm:
            k.dma('sp', obt[b][:], obT[:, rows].rearrange("(a p) t -> p a t", p=128), w=[f'obt{b}'])
        else:
            k.dma('sp', oc[b][:, 512:1024], ob[rows, :], w=[f'oB{b}'])
        k.dma('sp', ht[b][:], hin[rows, :], w=[f'ht{b}'])
        if glu:
            y = oc[b][:, 512:1024]
            k.cp('dve', yb[b][:], y, [f'oB{b}'], [f'yb{b}'])
            for kc in range(4):
                k.tr(psT[b2][:, kc * 128:(kc + 1) * 128], yb[b][:, kc * 128:(kc + 1) * 128], k.identb[:], [f'yb{b}'], [f'psT{b2}'])
            k.cp('act', yT[b][:], psT[b2][:, 0:512].rearrange("p (k t) -> p k t", k=4), [f'psT{b2}'], [f'yT{b}'])
            for kc in range(4):
                k.mm(psG[b2][:], yT[b][:, kc, :], Wglu[:, kc, :], kc == 0, kc == 3, [f'yT{b}', f'Wglu{kc}'], [f'psG{b2}'])
            k.act(t1[b][:], y, AF.Square, [f'oB{b}'], [f't1{b}'])
            k.act(t1[b][:], t1[b][:], AF.Copy, [f't1{b}'], [f't1{b}'], scale=0.044715, bias=1.0)
            k.tt('pool', t1[b][:], t1[b][:], y, ALU.mult, [f't1{b}', f'oB{b}'], [f't1{b}'])
            k.act(t1[b][:], t1[b][:], AF.Sigmoid, [f't1{b}'], [f't1{b}'], scale=GELU_C)
            yield
            k.tt('dve', zs[b][:], psG[b2][:], bgbc[:], ALU.add, [f'psG{b2}', 'bgbc'], [f'zs{b}'])
            k.act(zs[b][:], zs[b][:], AF.Sigmoid, [f'zs{b}'], [f'zs{b}'])
            k.tt('dve', t2[b][:], t1[b][:], zs[b][:], ALU.mult, [f't1{b}', f'zs{b}'], [f't2{b}'])
            k.tt('dve', y, y, t2[b][:], ALU.mult, [f'oB{b}', f't2{b}'], [f'oB{b}'])
        if ob_fm:
            k.cp('dve', ocb[b][:, 0:512], oc[b][:, 0:512], [f'oA{b}'], [f'ocb{b}'])
            for kc in range(4):
                k.tr(psT[b2][:, kc * 128:(kc + 1) * 128], ocb[b][:, kc * 128:(kc + 1) * 128], k.identb[:], [f'ocb{b}'], [f'psT{b2}'])
            k.cp('act', oT[b][:, 0:4, :], psT[b2][:, 0:512].rearrange("p (k t) -> p k t", k=4), [f'psT{b2}'], [f'oT{b}'])
            k.cp('pool', oT[b][:, 4:8, :], obt[b][:], [f'obt{b}'], [f'oTb{b}'])
        else:
            k.cp('dve', ocb[b][:], oc[b][:], [f'oA{b}', f'oB{b}'], [f'ocb{b}'])
            for kc in range(KC):
                k.tr(psT[b2][:, kc * 128:(kc + 1) * 128], ocb[b][:, kc * 128:(kc + 1) * 128], k.identb[:], [f'ocb{b}'], [f'psT{b2}'])
            k.cp('act', oT[b][:], psT[b2][:].rearrange("p (k t) -> p k t", k=KC), [f'psT{b2}'], [f'oT{b}'])
        yield
        for cg in range(2):
            pm = 2 * b2 + cg
            for kc in range(KC):
                ok_ = f'oTb{b}' if (ob_fm and kc >= 4) else f'oT{b}'
                k.mm(psM[pm][:], oT[b][:, kc, :], Wout[:, kc, cg * 512:(cg + 1) * 512], kc == 0, kc == KC - 1,
                     [ok_, f'Wout{kc}'], [f'psM{pm}'])
        post_norm_res(k, [psM[2 * b2][:], psM[2 * b2 + 1][:]], [f'psM{2 * b2}', f'psM{2 * b2 + 1}'], ht[b], f'ht{b}',
                      g1bc, 'g1bc', [tmp[b2][0][:], tmp[b2][1][:]], [f'tmp{b2}0', f'tmp{b2}1'], ss2[b], rstd[b][:], junk, f'pn{b}')
        k.dma('pool', hout[rows, :], ht[b][:], r=[f'ht{b}'], final=True)

    pipeline(tile, NT)
    return k.finish()


def build_C3(NTOK, k=None):
    k = k or K()
    NB = NTOK // 512
    DFF = 4096
    FC = DFF // 128
    hin = k.din("hin", [NTOK, D])
    w1 = k.din("w1", [D, DFF])
    w2 = k.din("w2", [DFF, D])
    g4 = k.din("g4", [D])
    g5 = k.din("g5", [D])
    ident_d = k.din("ident", [128, 128])
    hout = k.dout("hout", [NTOK, D])
    k.consts(ident_d)
    g4c = k.gain_cols("g4c", g4)
    g5bc = k.bcast_row("g5bc", g5, D)
    W1 = k.load_weight("W1", w1, KC, DFF, gcol=g4c, gkey='g4c', stage_cols=512)
    W2 = k.load_weight("W2", w2, FC, D, stage_cols=512)
    ht = [k.sb(f"ht{i}", [128, D]) for i in range(4)]
    xn = [k.sb(f"xn{i}", [128, D], BF16) for i in range(2)]
    xT = k.sb("xT", [128, KC, 512], BF16)
    AT = k.sb("AT", [128, FC, 512], BF16)
    sq = [k.sb(f"sq{i}", [128, 512]) for i in range(2)]
    junk = k.sb("junk", [128, D], BF16)
    ss = [k.sb(f"ss{i}", [128, 1]) for i in range(2)]
    ss2 = [k.sb(f"ss2{i}", [128, 2]) for i in range(2)]
    rstd = [k.sb(f"rstd{i}", [128, 1]) for i in range(2)]
    rstd2 = [k.sb(f"rstdb{i}", [128, 1]) for i in range(2)]
    psT = k.ps("psT", [128, D], BF16)
    psU = [k.ps(f"psU{i}", [128, 512]) for i in range(3)]
    psD = [k.ps(f"psD{i}", [128, 512]) for i in range(4)]
    nu = 0
    for blk in range(NB):
        for tt in range(4):
            i = blk * 4 + tt
            b = i % 2
            rows = slice(i * 128, (i + 1) * 128)
            k.dma('sp', ht[tt][:], hin[rows, :], w=[f'ht{tt}'])
            norm_T(k, ht[tt][:], f'ht{tt}', xn[b][:], f'xn{b}', xT[:, :, tt * 128:(tt + 1) * 128], 'xT', psT[:], 'psT',
                   ss[b][:], rstd[b][:], junk[:], f'n{b}')
        for fc in range(FC):
            pu = nu % 3
            nu += 1
            for kc in range(KC):
                k.mm(psU[pu][:], W1[:, kc, fc * 128:(fc + 1) * 128], xT[:, kc, :], kc == 0, kc == KC - 1,
                     [f'W1{kc}', 'xT'], [f'psU{pu}'])
            sb_ = fc % 2
            k.act(sq[sb_][:], psU[pu][:], AF.Square, [f'psU{pu}'], [f'sq{sb_}'])
            k.stt(AT[:, fc, :], psU[pu][:], 0.0, sq[sb_][:], ALU.is_gt, ALU.mult, [f'psU{pu}', f'sq{sb_}'], ['AT'])
        for tt in range(4):
            i = blk * 4 + tt
            b = i % 2
            rows = slice(i * 128, (i + 1) * 128)
            for cg in range(2):
                pd = 2 * b + cg
                for fc in range(FC):
                    k.mm(psD[pd][:], AT[:, fc, tt * 128:(tt + 1) * 128], W2[:, fc, cg * 512:(cg + 1) * 512],
                         fc == 0, fc == FC - 1, ['AT', f'W2{fc}'], [f'psD{pd}'])
            post_norm_res(k, [psD[2 * b][:], psD[2 * b + 1][:]], [f'psD{2 * b}', f'psD{2 * b + 1}'], ht[tt], f'ht{tt}',
                          g5bc, 'g5bc', [sq[0][:], sq[1][:]], ['sq0', 'sq1'], ss2[b], rstd2[b][:], junk, f'pn{b}')
            k.dma('pool', hout[rows, :], ht[tt][:], r=[f'ht{tt}'], final=True)
    return k.finish()


def build_C2(NTOK, k=None):
    k = k or K()
    NB = NTOK // 512
    MEM = 256
    hin = k.din("hin", [NTOK, D])
    mem = k.din("mem", [MEM, D])
    wq = k.din("wq", [D, D])
    wk = k.din("wk", [D, D])
    wv = k.din("wv", [D, D])
    wo = k.din("wo", [D, D])
    g2 = k.din("g2", [D])
    g3 = k.din("g3", [D])
    g6 = k.din("g6", [D])
    ident_d = k.din("ident", [128, 128])
    hout = k.dout("hout", [NTOK, D])
    k.consts(ident_d)
    g2c = k.gain_cols("g2c", g2)
    g6c = k.gain_cols("g6c", g6)
    g3bc = k.bcast_row("g3bc", g3, D)
    Wk = k.load_weight("Wk", wk, KC, D, gcol=g6c, gkey='g6c', stage_cols=1024)
    Wv = k.load_weight("Wv", wv, KC, D, gcol=g6c, gkey='g6c', stage_cols=1024)
    Wq = k.load_weight("Wq", wq, KC, D, gcol=g2c, gkey='g2c', stage_cols=1024)
    Wo = k.load_weight("Wo", wo, KC, D, stage_cols=1024)
    ht = [k.sb(f"ht{i}", [128, D]) for i in range(8)]
    xn = [k.sb(f"xn{i}", [128, D], BF16) for i in range(2)]
    xT = [k.sb(f"xT{i}", [128, KC, 512], BF16) for i in range(2)]
    memT = k.sb("memT", [128, KC, MEM], BF16)
    KT = k.sb("KT", [128, KC, MEM], BF16)
    V = k.sb("V", [128, 2, D], BF16)
    QT = [k.sb(f"QT{i}", [128, KC, 512], BF16) for i in range(2)]
    Pm = [k.sb(f"Pm{i}", [128, 4, MEM], BF16) for i in range(3)]
    Pn = [k.sb(f"Pn{i}", [128, 4, MEM], BF16) for i in range(3)]
    PT = [k.sb(f"PT{i}", [128, 8, 128], BF16) for i in range(3)]
    OT = [k.sb(f"OT{i}", [128, KC, 128], BF16) for i in range(3)]
    tmp = [k.sb(f"tmp{i}", [128, 512]) for i in range(2)]
    junk = k.sb("junk", [128, D], BF16)
    ss = [k.sb(f"ss{i}", [128, 1]) for i in range(2)]
    ss2 = [k.sb(f"ss2{i}", [128, 2]) for i in range(2)]
    rstd = [k.sb(f"rstd{i}", [128, 1]) for i in range(2)]
    rstd2 = [k.sb(f"rstdb{i}", [128, 1]) for i in range(2)]
    mx = [k.sb(f"mx{i}", [128, 4]) for i in range(3)]
    sm = [k.sb(f"sm{i}", [128, 4]) for i in range(3)]
    psT = k.ps("psT", [128, D], BF16)
    psA = k.ps("psA", [128, 1024])
    psS = k.ps("psS", [128, 1024])
    psX = k.ps("psX", [128, 1024])
    for mt in range(2):
        k.dma('sp', ht[mt][:], mem[mt * 128:(mt + 1) * 128, :], w=[f'ht{mt}'])
        norm_T(k, ht[mt][:], f'ht{mt}', xn[mt][:], f'xn{mt}', memT[:, :, mt * 128:(mt + 1) * 128], 'memT', psT[:], 'psT',
               ss[mt][:], rstd[mt][:], junk[:], f'n{mt}')
    for cc in range(KC):
        pa = cc % 2
        for kc in range(KC):
            k.mm(psA[:, pa * 512:pa * 512 + MEM], Wk[:, kc, cc * 128:(cc + 1) * 128], memT[:, kc, :], kc == 0, kc == KC - 1,
                 [f'Wk{kc}', 'memT'], [f'psA{pa}'])
        k.cp('act' if cc % 2 else 'dve', KT[:, cc, :], psA[:, pa * 512:pa * 512 + MEM], [f'psA{pa}'], [f'KT{cc}'])
    for mt in range(2):
        for cg in range(2):
            for kc in range(KC):
                k.mm(psX[:, cg * 512:(cg + 1) * 512], memT[:, kc, mt * 128:(mt + 1) * 128], Wv[:, kc, cg * 512:(cg + 1) * 512],
                     kc == 0, kc == KC - 1, ['memT', f'Wv{kc}'], [f'psX{cg}'])
            k.cp('act' if cg else 'dve', V[:, mt, cg * 512:(cg + 1) * 512], psX[:, cg * 512:(cg + 1) * 512], [f'psX{cg}'], [f'V{mt}{cg}'])
    def tile(i):
        blk, tt = divmod(i, 4)
        xb = blk % 2
        b = i % 3
        rows = slice(i * 128, (i + 1) * 128)
        tsl = slice(tt * 128, (tt + 1) * 128)
        hb = xb * 4 + tt
        if tt == 0:
            for t2_ in range(4):
                i2 = blk * 4 + t2_
                b2 = i2 % 2
                hb2 = xb * 4 + t2_
                k.dma('sp', ht[hb2][:], hin[i2 * 128:(i2 + 1) * 128, :], w=[f'ht{hb2}'])
                norm_T(k, ht[hb2][:], f'ht{hb2}', xn[b2][:], f'xn{b2}', xT[xb][:, :, t2_ * 128:(t2_ + 1) * 128], f'xT{xb}', psT[:], 'psT',
                       ss[b2][:], rstd[b2][:], junk[:], f'n{b2}')
            for cc in range(KC):
                pa = cc % 2
                for kc in range(KC):
                    k.mm(psA[:, pa * 512:(pa + 1) * 512], Wq[:, kc, cc * 128:(cc + 1) * 128], xT[xb][:, kc, :], kc == 0, kc == KC - 1,
                         [f'Wq{kc}', f'xT{xb}'], [f'psA{pa}'])
                k.cp('act' if cc % 2 else 'dve', QT[xb][:, cc, :], psA[:, pa * 512:(pa + 1) * 512], [f'psA{pa}'], [f'QT{xb}{cc}'])
        for h in range(4):
            sb_ = h // 2
            for j in range(2):
                cc = 2 * h + j
                k.mm(psS[:, h * MEM:(h + 1) * MEM], QT[xb][:, cc, tsl], KT[:, cc, :], j == 0, j == 1,
                     [f'QT{xb}{cc}', f'KT{cc}'], [f'psS{sb_}'])
        k.P.op('dve', lambda e, b=b: e.tensor_reduce(out=mx[b][:], in_=psS[:].rearrange("p (h m) -> p h m", h=4),
                                                    axis=AX.X, op=ALU.max),
               reads=['psS0', 'psS1'], writes=[f'mx{b}'])
        k.ts('dve', mx[b][:], mx[b][:], -1.0 / 16.0, None, ALU.mult, None, [f'mx{b}'], [f'mx{b}'])
        for h in range(4):
            k.act(Pm[b][:, h, :], psS[:, h * MEM:(h + 1) * MEM], AF.Exp, [f'psS{h // 2}', f'mx{b}'], [f'Pm{b}', f'sm{b}'],
                  scale=1.0 / 16.0, bias=mx[b][:, h:h + 1], accum_out=sm[b][:, h:h + 1])
        k.recip(sm[b][:], sm[b][:], [f'sm{b}'], [f'sm{b}'])
        k.tt('dve', Pn[b][:], Pm[b][:], sm[b][:].unsqueeze(2).broadcast_to([128, 4, MEM]), ALU.mult,
             [f'Pm{b}', f'sm{b}'], [f'Pn{b}'])
        yield
        for h in range(4):
            for mt in range(2):
                k.tr(psT[:, (h * 2 + mt) * 128:(h * 2 + mt + 1) * 128], Pn[b][:, h, mt * 128:(mt + 1) * 128], k.identb[:],
                     [f'Pn{b}'], ['psT'])
        k.cp('act', PT[b][:], psT[:].rearrange("p (k t) -> p k t", k=8), ['psT'], [f'PT{b}'])
        for cc in range(KC):
            h = cc // 2
            pa = cc // 4
            for mt in range(2):
                k.mm(psA[:, cc * 128:(cc + 1) * 128], V[:, mt, cc * 128:(cc + 1) * 128], PT[b][:, h * 2 + mt, :],
                     mt == 0, mt == 1, [f'V{mt}{cc // 4}', f'PT{b}'], [f'psA{pa}'])
        k.cp('dve', OT[b][:, 0:4, :], psA[:, 0:512].rearrange("p (k t) -> p k t", k=4), ['psA0'], [f'OT{b}_0'])
        k.cp('act', OT[b][:, 4:8, :], psA[:, 512:1024].rearrange("p (k t) -> p k t", k=4), ['psA1'], [f'OT{b}_1'])
        yield
        for cg in range(2):
            for cc in range(KC):
                k.mm(psX[:, cg * 512:(cg + 1) * 512], OT[b][:, cc, :], Wo[:, cc, cg * 512:(cg + 1) * 512],
                     cc == 0, cc == KC - 1, [f'OT{b}_{cc // 4}', f'Wo{cc}'], [f'psX{cg}'])
        post_norm_res(k, [psX[:, 0:512], psX[:, 512:1024]], ['psX0', 'psX1'], ht[hb], f'ht{hb}',
                      g3bc, 'g3bc', [tmp[0][:], tmp[1][:]], ['tmp0', 'tmp1'], ss2[b % 2], rstd2[b % 2][:], junk, f'pn{b % 2}')
        k.dma('pool', hout[rows, :], ht[hb][:], r=[f'ht{hb}'], final=True)

    pipeline(tile, NTOK // 128)
    return k.finish()


def build_A2(NTOK, NC, fm, NF, k=None):
    k = k or K()
    NB = NTOK // 512
    x = k.din("x", [NTOK, D])
    gain = k.din("gain", [D])
    W = k.din("W", [D, NC])
    ident_d = k.din("ident", [128, 128])
    out = k.dout("out", [NTOK, NC])
    outT = k.dout("outT", [NF, NTOK])
    k.consts(ident_d)
    gc = k.gain_cols("gc", gain)
    Wb = k.load_weight("Wb", W, KC, NC, gcol=gc, gkey='gc', stage_cols=1408)
    cgs = [(c0, min(512, NC - c0)) for c0 in range(0, NC, 512)]
    xt = [k.sb(f"xt{i}", [128, D]) for i in range(2)]
    xn = [k.sb(f"xn{i}", [128, D], BF16) for i in range(2)]
    xT = [k.sb(f"xT{i}", [128, KC, 512], BF16) for i in range(2)]
    ot = [k.sb(f"ot{i}", [128, NC]) for i in range(2)]
    ft = [k.sb(f"ft{i}", [128, 512]) for i in range(2)]
    junk = k.sb("junk", [128, D], BF16)
    ss = [k.sb(f"ss{i}", [128, 1]) for i in range(2)]
    rstd = [k.sb(f"rstd{i}", [128, 1]) for i in range(2)]
    psT = k.ps("psT", [128, D], BF16)
    psO = [k.ps(f"psO{i}", [128, 512]) for i in range(4)]
    psF = [k.ps(f"psF{i}", [128, 512]) for i in range(2)]
    no = 0
    nf = 0
    for blk in range(NB):
        xb = blk % 2
        for tt in range(4):
            i = blk * 4 + tt
            b = i % 2
            k.dma('sp', xt[b][:], x[i * 128:(i + 1) * 128, :], w=[f'xt{b}'])
            norm_T(k, xt[b][:], f'xt{b}', xn[b][:], f'xn{b}', xT[xb][:, :, tt * 128:(tt + 1) * 128], f'xT{xb}', psT[:], 'psT',
                   ss[b][:], rstd[b][:], junk[:], f'n{b}')
        for tt in range(4):
            i = blk * 4 + tt
            b = i % 2
            for ci, (c0, cw) in enumerate(cgs):
                pb = no % 4
                no += 1
                for kc in range(KC):
                    k.mm(psO[pb][:, 0:cw], xT[xb][:, kc, tt * 128:(tt + 1) * 128], Wb[:, kc, c0:c0 + cw], kc == 0, kc == KC - 1,
                         [f'xT{xb}', f'Wb{kc}'], [f'psO{pb}'])
                k.cp('dve' if pb % 2 == 0 else 'act', ot[b][:, c0:c0 + cw], psO[pb][:, 0:cw], [f'psO{pb}'], [f'ot{b}_{pb % 2}'])
            k.dma('pool', out[i * 128:(i + 1) * 128, :], ot[b][:], r=[f'ot{b}_0', f'ot{b}_1'], final=True)
        for (c0, cw, r0) in fm:
            pf = nf % 2
            nf += 1
            for kc in range(KC):
                k.mm(psF[pf][0:cw, :], Wb[:, kc, c0:c0 + cw], xT[xb][:, kc, :], kc == 0, kc == KC - 1,
                     [f'Wb{kc}', f'xT{xb}'], [f'psF{pf}'])
            k.cp('dve' if pf == 0 else 'act', ft[pf][0:cw, :], psF[pf][0:cw, :], [f'psF{pf}'], [f'ft{pf}'])
            k.dma('pool', outT[r0:r0 + cw, blk * 512:(blk + 1) * 512], ft[pf][0:cw, :], r=[f'ft{pf}'], final=True)
    return k.finish()


def gen_GLA(L, k):
    NT = L // 128
    qT = k.din("qT", [128, L])
    kT = k.din("kT", [128, L])
    ktok = k.din("ktok", [L, 128])
    v = k.din("v", [L, 256])
    gate = k.din("gate", [L, 256])
    dlrT = k.din("dlrT", [16, L])
    w2 = k.din("w2", [16, 128])
    bdec = k.din("bdec", [1, 128])
    gn = k.din("gn", [256])
    triu_d = k.din("triu", [128, 128])
    trigt_d = k.din("trigt", [128, 128])
    oa = k.dout("oa", [L, 256])

    triu = k.sb("triu_s", [128, 128])
    trigt = k.sb("trigt_s", [128, 128])
    k.dma('sp', triu[:], triu_d, w=['triu'])
    k.dma('sp', trigt[:], trigt_d, w=['trigt'])
    w2s = k.sb("w2s", [16, 128])
    k.dma('sp', w2s[:], w2, w=['w2s'])
    bds = k.sb("bds", [1, 128])
    k.dma('sp', bds[:], bdec, w=['bds'])
    ones1 = k.sb("ones1", [1, 128])
    k.memset('dve', ones1[:], 1.0, ['ones1'])
    gnbc = k.bcast_row("gnbc", gn, 256)
    S = k.sb("S", [128, 128])
    k.memset('dve', S[:], 0.0, ['S'])
    NB = 2
    qTt = [k.sb(f"qTt{i}", [128, 128]) for i in range(NB)]
    kTt = [k.sb(f"kTt{i}", [128, 128]) for i in range(NB)]
    kt = [k.sb(f"kt{i}", [128, 128]) for i in range(NB)]
    vt = [k.sb(f"vt{i}", [128, 256]) for i in range(NB)]
    gt = [k.sb(f"gt{i}", [128, 256]) for i in range(NB)]
    dt_ = [k.sb(f"dt{i}", [16, 128]) for i in range(NB)]
    la = k.sb("la", [128, 128])
    EqT = k.sb("EqT", [128, 128])
    EkT = k.sb("EkT", [128, 128])
    Eks = k.sb("Eks", [128, 128])
    qin = k.sb("qin", [128, 128])
    kin = k.sb("kin", [128, 128])
    kst = k.sb("kst", [128, 128])
    sc = [k.sb(f"sc{i}", [128, 128]) for i in range(2)]
    osb = k.sb("osb", [128, 256])
    junk = k.sb("junk", [128, 128])
    ss = k.sb("ss", [128, 2])
    rs = k.sb("rs", [128, 2])
    sg = k.sb("sg", [128, 256])
    ot = [k.sb(f"ot{i}", [128, 256]) for i in range(NB)]
    psA = k.ps("psA", [128, 512])
    psB = k.ps("psB", [128, 512])
    psC = k.ps("psC", [128, 512])
    for i in range(NT):
        b = i % NB
        rows = slice(i * 128, (i + 1) * 128)
        k.dma('sp', qTt[b][:], qT[:, rows], w=[f'qTt{b}'])
        k.dma('sp', kTt[b][:], kT[:, rows], w=[f'kTt{b}'])
        k.dma('sp', kt[b][:], ktok[rows, :], w=[f'kt{b}'])
        k.dma('sp', vt[b][:], v[rows, :], w=[f'vt{b}'])
        k.dma('sp', gt[b][:], gate[rows, :], w=[f'gt{b}'])
        k.dma('sp', dt_[b][:], dlrT[:, rows], w=[f'dt{b}'])
        k.mm(psA[:, 0:128], dt_[b][:], w2s[:], True, False, [f'dt{b}', 'w2s'], ['psA'])
        k.mm(psA[:, 0:128], ones1[:], bds[:], False, True, ['ones1', 'bds'], ['psA'])
        k.act(la[:], psA[:, 0:128], AF.Exp, ['psA'], ['la'], scale=-1.0)
        k.act(la[:], la[:], AF.Ln, ['la'], ['la'], bias=1.0)
        k.ts('dve', la[:], la[:], -1.0 / 16.0, None, ALU.mult, None, ['la'], ['la'])
        k.mm(psA[:, 128:256], la[:], triu[:], True, True, ['la', 'triu'], ['psA'])
        k.mm(psA[:, 256:384], trigt[:], la[:], True, True, ['la', 'trigt'], ['psA'])
        k.act(EqT[:], psA[:, 128:256], AF.Exp, ['psA'], ['EqT'])
        k.act(EkT[:], psA[:, 128:256], AF.Exp, ['psA'], ['EkT'], scale=-1.0)
        k.act(Eks[:], psA[:, 256:384], AF.Exp, ['psA'], ['Eks'])
        k.stt(qin[:], qTt[b][:], 0.125, EqT[:], ALU.mult, ALU.mult, [f'qTt{b}', 'EqT'], ['qin'])
        k.tt('pool', kin[:], kTt[b][:], EkT[:], ALU.mult, [f'kTt{b}', 'EkT'], ['kin'])
        k.tt('pool', kst[:], kt[b][:], Eks[:], ALU.mult, [f'kt{b}', 'Eks'], ['kst'])
        for h in range(2):
            hp = slice(h * 64, (h + 1) * 64)
            k.mm(psB[:, h * 128:(h + 1) * 128], kin[hp, :], qin[hp, :], True, True, ['kin', 'qin'], ['psB'])
            k.tt('dve', sc[h][:], psB[:, h * 128:(h + 1) * 128], triu[:], ALU.mult, ['psB', 'triu'], [f'sc{h}'])
            k.mm(psC[:, h * 128:(h + 1) * 128], sc[h][:], vt[b][:, h * 128:(h + 1) * 128], True, False,
                 [f'sc{h}', f'vt{b}'], ['psC'])
            k.mm(psC[:, h * 128:(h + 1) * 128], qin[hp, :], S[hp, :], False, True, ['qin', 'S'], ['psC'])
        k.mm(psC[:, 256:512], kst[:], vt[b][:], True, True, ['kst', f'vt{b}'], ['psC'])
        for h in range(2):
            hp = slice(h * 64, (h + 1) * 64)
            k.stt(S[hp, :], S[hp, :], EqT[hp, 127:128], psC[hp, 256 + h * 128:256 + (h + 1) * 128], ALU.mult, ALU.add,
                  ['S', 'EqT', 'psC'], ['S'])
        for h in range(2):
            k.act(junk[:], psC[:, h * 128:(h + 1) * 128], AF.Square, ['psC'], ['junk', 'ss'], accum_out=ss[:, h:h + 1])
        k.ts('dve', rs[:], ss[:], 1.0 / 128.0, EPS, ALU.mult, ALU.add, ['ss'], ['rs'])
        k.act(rs[:], rs[:], AF.Ln, ['rs'], ['rs'])
        k.act(rs[:], rs[:], AF.Exp, ['rs'], ['rs'], scale=-0.5)
        k.act(sg[:], gt[b][:], AF.Exp, [f'gt{b}'], ['sg'], scale=-1.0)
        k.ts('dve', sg[:], sg[:], 1.0, None, ALU.add, None, ['sg'], ['sg'])
        k.recip(sg[:], sg[:], ['sg'], ['sg'])
        k.tt('pool', sg[:], sg[:], gt[b][:], ALU.mult, ['sg', f'gt{b}'], ['sg'])
        for h in range(2):
            hs = slice(h * 128, (h + 1) * 128)
            k.stt(osb[:, hs], psC[:, hs], rs[:, h:h + 1], gnbc[:, hs], ALU.mult, ALU.mult, ['psC', 'rs', 'gnbc'], ['osb'])
        k.tt('pool', ot[b][:], osb[:], sg[:], ALU.mult, ['osb', 'sg'], [f'ot{b}'])
        k.dma('pool', oa[rows, :], ot[b][:], r=[f'ot{b}'], final=True)
        yield


def build_GLA(L, k=None):
    k = k or K()
    for _ in gen_GLA(L, k):
        pass
    return k.finish()


TWO_PI = 2.0 * math.pi
C1 = 6.28125
C2 = TWO_PI - 6.28125
PI_LO = 3.1415925


def range_sincos(k, x, xkey, shape, s_out, c_out, skey, ckey, pfx):
    if not hasattr(k, 'rr_cache'):
        k.rr_cache = {}
    if pfx not in k.rr_cache:
        k.rr_cache[pfx] = (k.sb(pfx + "kf", shape), k.sb(pfx + "ki", shape, I32), k.sb(pfx + "r", shape), k.sb(pfx + "m", shape))
    kf, ki, r, m = k.rr_cache[pfx]
    a = lambda t: t[:]
    K1, K2, K3, K4 = pfx + 'kf', pfx + 'ki', pfx + 'r', pfx + 'm'
    k.ts('dve', a(kf), x, 1.0 / TWO_PI, None, ALU.mult, None, [xkey], [K1])
    k.cp('dve', a(ki), a(kf), [K1], [K2])
    k.cp('dve', a(kf), a(ki), [K2], [K1])
    k.stt(a(r), a(kf), -C1, x, ALU.mult, ALU.add, [K1, xkey], [K3])
    k.stt(a(r), a(kf), -C2, a(r), ALU.mult, ALU.add, [K1, K3], [K3])
    k.ts('dve', a(m), a(r), math.pi, -TWO_PI, ALU.is_gt, ALU.mult, [K3], [K4])
    k.tt('dve', a(r), a(r), a(m), ALU.add, [K3, K4], [K3])
    k.ts('dve', a(m), a(r), -math.pi, TWO_PI, ALU.is_lt, ALU.mult, [K3], [K4])
    k.tt('dve', a(r), a(r), a(m), ALU.add, [K3, K4], [K3])
    k.ts('dve', a(kf), a(r), PI_LO, -PI_LO, ALU.min, ALU.max, [K3], [K1])
    k.act(s_out, a(kf), AF.Sin, [K1], [skey])
    k.ts('dve', a(r), a(r), math.pi / 2, None, ALU.add, None, [K3], [K3])
    k.ts('dve', a(m), a(r), math.pi, -TWO_PI, ALU.is_gt, ALU.mult, [K3], [K4])
    k.tt('dve', a(r), a(r), a(m), ALU.add, [K3, K4], [K3])
    k.ts('dve', a(kf), a(r), PI_LO, -PI_LO, ALU.min, ALU.max, [K3], [K1])
    k.act(c_out, a(kf), AF.Sin, [K1], [ckey])


def gen_S5(L, k):
    NT = L // 128
    NS = 1024
    uT = k.din("uT", [256, L])
    u = k.din("u", [L, 256])
    lam_re = k.din("lam_re", [NS])
    lam_im = k.din("lam_im", [NS])
    lstep = k.din("lstep", [NS])
    Bre = k.din("Bre", [2, 128, 512])
    Bim = k.din("Bim", [2, 128, 512])
    Cre = k.din("Cre", [8, 128, 32])
    Cim = k.din("Cim", [8, 128, 32])
    dsk = k.din("dsk", [256])
    triu_d = k.din("triu", [128, 128])
    iop_d = k.din("iota_p", [128, 1])
    iof_d = k.din("iota_f", [128, 128])
    y = k.dout("y", [L, 256])

    k.push_scope([("triu_s", [128, 128], F32), ("dbc", [128, 256], F32), ("BBr", [128, 2, 512], F32), ("BBi", [128, 2, 512], F32),
                  ("Pr", [128, NS], F32), ("Pi", [128, NS], F32), ("Qr", [128, 8, 128], F32), ("Qi", [128, 8, 128], F32),
                  ("L128r", [128, 8], F32), ("L128i", [128, 8], F32), ("Cr", [128, 8, 32], F32), ("nCi", [128, 8, 32], F32),
                  ("car_r", [128, 8], F32), ("car_i", [128, 8], F32)])
    triu = k.sb("triu_s", [128, 128])
    k.dma('sp', triu[:], triu_d, w=['triu'])
    iop = k.sb("iop", [128, 1])
    k.dma('sp', iop[:], iop_d, w=['iop'])
    negp = k.sb("negp", [128, 1])
    k.ts('dve', negp[:], iop[:], -1.0, None, ALU.mult, None, ['iop'], ['negp'])
    iof = k.sb("iof", [128, 128])
    k.dma('sp', iof[:], iof_d, w=['iof'])
    dbc = k.bcast_row("dbc", dsk, 256)
    R = [128, NS]
    lr = k.bcast_row("lr", lam_re, NS)
    li = k.bcast_row("li", lam_im, NS)
    dl = k.bcast_row("dl", lstep, NS)
    k.ts('dve', lr[:], lr[:], -1e-4, None, ALU.min, None, ['lr'], ['lr'])
    k.act(dl[:], dl[:], AF.Exp, ['dl'], ['dl'])
    a_ = k.sb("a_", R)
    th = k.sb("th", R)
    k.tt('dve', a_[:], lr[:], dl[:], ALU.mult, ['lr', 'dl'], ['a_'])
    k.tt('dve', th[:], li[:], dl[:], ALU.mult, ['li', 'dl'], ['th'])
    sn = k.sb("sn", R)
    cs = k.sb("cs", R)
    range_sincos(k, th[:], 'th', R, sn[:], cs[:], 'sn', 'cs', 'rr_')
    ea = k.sb("ea", R)
    k.act(ea[:], a_[:], AF.Exp, ['a_'], ['ea'])
    nr = k.sb("nr", R)
    ni = k.sb("ni", R)
    k.tt('dve', nr[:], ea[:], cs[:], ALU.mult, ['ea', 'cs'], ['nr'])
    k.ts('dve', nr[:], nr[:], -1.0, None, ALU.add, None, ['nr'], ['nr'])
    k.tt('dve', ni[:], ea[:], sn[:], ALU.mult, ['ea', 'sn'], ['ni'])
    den = k.sb("den", R)
    t0 = k.sb("t0", R)
    k.tt('dve', den[:], lr[:], lr[:], ALU.mult, ['lr'], ['den'])
    k.tt('dve', t0[:], li[:], li[:], ALU.mult, ['li'], ['t0'])
    k.tt('dve', den[:], den[:], t0[:], ALU.add, ['den', 't0'], ['den'])
    k.recip(den[:], den[:], ['den'], ['den'])
    gr = k.sb("gr", R)
    gi = k.sb("gi", R)
    k.tt('dve', gr[:], nr[:], lr[:], ALU.mult, ['nr', 'lr'], ['gr'])
    k.tt('dve', t0[:], ni[:], li[:], ALU.mult, ['ni', 'li'], ['t0'])
    k.tt('dve', gr[:], gr[:], t0[:], ALU.add, ['gr', 't0'], ['gr'])
    k.tt('dve', gr[:], gr[:], den[:], ALU.mult, ['gr', 'den'], ['gr'])
    k.tt('dve', gi[:], ni[:], lr[:], ALU.mult, ['ni', 'lr'], ['gi'])
    k.tt('dve', t0[:], nr[:], li[:], ALU.mult, ['nr', 'li'], ['t0'])
    k.tt('dve', gi[:], gi[:], t0[:], ALU.subtract, ['gi', 't0'], ['gi'])
    k.tt('dve', gi[:], gi[:], den[:], ALU.mult, ['gi', 'den'], ['gi'])
    Br = k.sb("Br", [128, 2, 512])
    Bi = k.sb("Bi", [128, 2, 512])
    BBr = k.sb("BBr", [128, 2, 512])
    BBi = k.sb("BBi", [128, 2, 512])
    for hc in range(2):
        k.dma('sp', Br[:, hc, :], Bre[hc], w=[f'Br{hc}'])
        k.dma('sp', Bi[:, hc, :], Bim[hc], w=[f'Bi{hc}'])
    grv = gr[:].rearrange("p (h n) -> p h n", h=2)
    giv = gi[:].rearrange("p (h n) -> p h n", h=2)
    t0v = t0[:].rearrange("p (h n) -> p h n", h=2)
    BK = ['Br0', 'Br1', 'Bi0', 'Bi1']
    k.tt('dve', BBr[:], grv, Br[:], ALU.mult, ['gr'] + BK, ['BBr'])
    k.tt('dve', t0v, giv, Bi[:], ALU.mult, ['gi'] + BK, ['t0'])
    k.tt('dve', BBr[:], BBr[:], t0v, ALU.subtract, ['BBr', 't0'], ['BBr'])
    k.tt('dve', BBi[:], grv, Bi[:], ALU.mult, ['gr'] + BK, ['BBi'])
    k.tt('dve', t0v, giv, Br[:], ALU.mult, ['gi'] + BK, ['t0'])
    k.tt('dve', BBi[:], BBi[:], t0v, ALU.add, ['BBi', 't0'], ['BBi'])
    ang = k.sb("ang", R)
    k.ts('dve', ang[:], th[:], iop[:, 0:1], None, ALU.mult, None, ['th', 'iop'], ['ang'])
    Pr = k.sb("Pr", R)
    Pi = k.sb("Pi", R)
    range_sincos(k, ang[:], 'ang', R, sn[:], cs[:], 'sn', 'cs', 'rr_')
    k.act(ea[:], a_[:], AF.Exp, ['a_', 'negp'], ['ea'], scale=negp[:, 0:1])
    k.tt('dve', Pr[:], ea[:], cs[:], ALU.mult, ['ea', 'cs'], ['Pr'])
    k.stt(Pi[:], ea[:], -1.0, sn[:], ALU.mult, ALU.mult, ['ea', 'sn'], ['Pi'])
    Cs = [128, 8]
    lrc = k.sb("lrc", Cs)
    lic = k.sb("lic", Cs)
    dlc = k.sb("dlc", Cs)
    cv = lambda d: d.rearrange("(blk p) -> p blk", p=128)
    k.dma('sp', lrc[:], cv(lam_re), w=['lrc'], allow_slow_non_contiguous=True)
    k.dma('sp', lic[:], cv(lam_im), w=['lic'], allow_slow_non_contiguous=True)
    k.dma('sp', dlc[:], cv(lstep), w=['dlc'], allow_slow_non_contiguous=True)
    k.ts('dve', lrc[:], lrc[:], -1e-4, None, ALU.min, None, ['lrc'], ['lrc'])
    k.act(dlc[:], dlc[:], AF.Exp, ['dlc'], ['dlc'])
    ac = k.sb("ac", Cs)
    thc = k.sb("thc", Cs)
    k.tt('dve', ac[:], lrc[:], dlc[:], ALU.mult, ['lrc', 'dlc'], ['ac'])
    k.tt('dve', thc[:], lic[:], dlc[:], ALU.mult, ['lic', 'dlc'], ['thc'])
    Qr = k.sb("Qr", [128, 8, 128])
    Qi = k.sb("Qi", [128, 8, 128])
    angv = ang[:].rearrange("p (b t) -> p b t", b=8)
    eav = ea[:].rearrange("p (b t) -> p b t", b=8)
    for blk in range(8):
        k.ts('dve', angv[:, blk, :], iof[:], thc[:, blk:blk + 1], None, ALU.mult, None, ['iof', 'thc'], ['ang'])
    range_sincos(k, ang[:], 'ang', R, sn[:], cs[:], 'sn', 'cs', 'rr_')
    for blk in range(8):
        k.act(eav[:, blk, :], iof[:], AF.Exp, ['iof', 'ac'], ['ea'], scale=ac[:, blk:blk + 1])
    k.tt('dve', Qr[:].rearrange("p b t -> p (b t)"), ea[:], cs[:], ALU.mult, ['ea', 'cs'], ['Qr'])
    k.tt('dve', Qi[:].rearrange("p b t -> p (b t)"), ea[:], sn[:], ALU.mult, ['ea', 'sn'], ['Qi'])
    a128 = k.sb("a128", Cs)
    s128 = k.sb("s128", Cs)
    c128 = k.sb("c128", Cs)
    L128r = k.sb("L128r", Cs)
    L128i = k.sb("L128i", Cs)
    k.ts('dve', a128[:], thc[:], 128.0, None, ALU.mult, None, ['thc'], ['a128'])
    range_sincos(k, a128[:], 'a128', Cs, s128[:], c128[:], 's128', 'c128', 'rc_')
    k.act(a128[:], ac[:], AF.Exp, ['ac', 's128', 'c128'], ['a128'], scale=128.0)
    k.tt('dve', L128r[:], a128[:], c128[:], ALU.mult, ['a128', 'c128'], ['L128r'])
    k.tt('dve', L128i[:], a128[:], s128[:], ALU.mult, ['a128', 's128'], ['L128i'])
    Cr = k.sb("Cr", [128, 8, 32])
    nCi = k.sb("nCi", [128, 8, 32])
    k.dma('sp', Cr[:], Cre.rearrange("b p c -> p b c"), w=['Cr'])
    k.dma('sp', nCi[:], Cim.rearrange("b p c -> p b c"), w=['nCi'])
    k.ts('dve', nCi[:], nCi[:], -1.0, None, ALU.mult, None, ['nCi'], ['nCi'])
    car_r = k.sb("car_r", Cs)
    car_i = k.sb("car_i", Cs)
    k.memset('dve', car_r[:], 0.0, ['car_r0', 'car_r1'])
    k.memset('dve', car_i[:], 0.0, ['car_i0', 'car_i1'])
    k.pop_scope()
    if hasattr(k, 'rr_cache'):
        del k.rr_cache
    uTt = [k.sb(f"uTt{i}", [128, 2, 128]) for i in range(2)]
    ut = [k.sb(f"ut{i}", [128, 256]) for i in range(4)]
    yo = [k.sb(f"yo{i}", [128, 256]) for i in range(2)]
    def T4(nm):
        return [[k.sb(f"{nm}{p}{h}", [128, 512]) for h in range(2)] for p in range(2)]
    m1, m2, m3, m4, Xtr, Xti = T4("m1_"), T4("m2_"), T4("m3_"), T4("m4_"), T4("Xtr"), T4("Xti")
    def T3(nm):
        return [[k.sb(f"{nm}{p}{h}", [128, 4, 128]) for h in range(2)] for p in range(2)]
    Gr, Gi, Hr, Hi = T3("Gr"), T3("Gi"), T3("Hr"), T3("Hi")
    cc1 = [k.sb(f"cc1_{h}", [128, 4]) for h in range(2)]
    cc2 = [k.sb(f"cc2_{h}", [128, 4]) for h in range(2)]
    psX = [[k.ps(f"psX{h}{c}", [128, 512]) for c in range(2)] for h in range(2)]
    psY = k.ps("psY", [128, 512])
    fl = lambda t: t[:].rearrange("p b t -> p (b t)")

    def tile(i):
        b = i % 2
        rows = slice(i * 128, (i + 1) * 128)
        K_ = lambda nm, hc: f'{nm}{b}{hc}'
        for hc in range(2):
            k.dma('sp', uTt[b][:, hc, :], uT[hc * 128:(hc + 1) * 128, rows], w=[f'uTt{b}{hc}'])
        b4 = i % 4
        k.dma('sp', ut[b4][:], u[rows, :], w=[f'ut{b4}'])
        for hc in range(2):
            k.mm(psX[hc][0][:], uTt[b][:, hc, :], BBr[:, hc, :], True, True, [f'uTt{b}{hc}', 'BBr'], [f'psX{hc}0'])
            k.mm(psX[hc][1][:], uTt[b][:, hc, :], BBi[:, hc, :], True, True, [f'uTt{b}{hc}', 'BBi'], [f'psX{hc}1'])
        for hc in range(2):
            cs_ = slice(hc * 512, (hc + 1) * 512)
            k.tt('dve', m1[b][hc][:], psX[hc][0][:], Pr[:, cs_], ALU.mult, [f'psX{hc}0', 'Pr'], [K_('m1', hc)])
            k.tt('dve', m2[b][hc][:], psX[hc][1][:], Pi[:, cs_], ALU.mult, [f'psX{hc}1', 'Pi'], [K_('m2', hc)])
            k.tt('pool', Xtr[b][hc][:], m1[b][hc][:], m2[b][hc][:], ALU.subtract, [K_('m1', hc), K_('m2', hc)], [K_('Xtr', hc)])
            k.tt('dve', m3[b][hc][:], psX[hc][0][:], Pi[:, cs_], ALU.mult, [f'psX{hc}0', 'Pi'], [K_('m3', hc)])
            k.tt('dve', m4[b][hc][:], psX[hc][1][:], Pr[:, cs_], ALU.mult, [f'psX{hc}1', 'Pr'], [K_('m4', hc)])
            k.tt('pool', Xti[b][hc][:], m3[b][hc][:], m4[b][hc][:], ALU.add, [K_('m3', hc), K_('m4', hc)], [K_('Xti', hc)])
        yield
        for hc in range(2):
            for nb in range(4):
                ns = slice(nb * 128, (nb + 1) * 128)
                k.mm(psX[hc][0][:, ns], Xtr[b][hc][:, ns], triu[:], True, True, [K_('Xtr', hc), 'triu'], [f'psX{hc}0'])
                k.mm(psX[hc][1][:, ns], Xti[b][hc][:, ns], triu[:], True, True, [K_('Xti', hc), 'triu'], [f'psX{hc}1'])
        for hc in range(2):
            bs = slice(hc * 4, (hc + 1) * 4)
            k.tt('dve', Gr[b][hc][:], psX[hc][0][:].rearrange("p (b t) -> p b t", b=4),
                 car_r[:, bs].unsqueeze(2).broadcast_to([128, 4, 128]), ALU.add, [f'psX{hc}0', f'car_r{hc}'], [K_('Gr', hc)])
            k.tt('dve', Gi[b][hc][:], psX[hc][1][:].rearrange("p (b t) -> p b t", b=4),
                 car_i[:, bs].unsqueeze(2).broadcast_to([128, 4, 128]), ALU.add, [f'psX{hc}1', f'car_i{hc}'], [K_('Gi', hc)])
            gr127 = Gr[b][hc][:, :, 127]
            gi127 = Gi[b][hc][:, :, 127]
            CK = [f'cc1{hc}', f'cc2{hc}']
            k.tt('pool', cc1[hc][:], L128r[:, bs], gr127, ALU.mult, ['L128r', K_('Gr', hc)], [CK[0]])
            k.tt('pool', cc2[hc][:], L128i[:, bs], gi127, ALU.mult, ['L128i', K_('Gi', hc)], [CK[1]])
            k.tt('pool', car_r[:, bs], cc1[hc][:], cc2[hc][:], ALU.subtract, CK, [f'car_r{hc}'])
            k.tt('pool', cc1[hc][:], L128r[:, bs], gi127, ALU.mult, ['L128r', K_('Gi', hc)], [CK[0]])
            k.tt('pool', cc2[hc][:], L128i[:, bs], gr127, ALU.mult, ['L128i', K_('Gr', hc)], [CK[1]])
            k.tt('pool', car_i[:, bs], cc1[hc][:], cc2[hc][:], ALU.add, CK, [f'car_i{hc}'])
        yield
        for hc in range(2):
            bs = slice(hc * 4, (hc + 1) * 4)
            qr = Qr[:, bs, :].rearrange("p b t -> p (b t)")
            qi = Qi[:, bs, :].rearrange("p b t -> p (b t)")
            k.tt('dve', m1[b][hc][:], fl(Gr[b][hc]), qr, ALU.mult, [K_('Gr', hc), 'Qr'], [K_('m1', hc)])
            k.tt('pool', m2[b][hc][:], fl(Gi[b][hc]), qi, ALU.mult, [K_('Gi', hc), 'Qi'], [K_('m2', hc)])
            k.tt('dve', fl(Hr[b][hc]), m1[b][hc][:], m2[b][hc][:], ALU.subtract, [K_('m1', hc), K_('m2', hc)], [K_('Hr', hc)])
            k.tt('dve', m3[b][hc][:], fl(Gi[b][hc]), qr, ALU.mult, [K_('Gi', hc), 'Qr'], [K_('m3', hc)])
            k.tt('pool', m4[b][hc][:], fl(Gr[b][hc]), qi, ALU.mult, [K_('Gr', hc), 'Qi'], [K_('m4', hc)])
            k.tt('dve', fl(Hi[b][hc]), m3[b][hc][:], m4[b][hc][:], ALU.add, [K_('m3', hc), K_('m4', hc)], [K_('Hi', hc)])
        yield
        for hc in range(2):
            for nb in range(4):
                blk = hc * 4 + nb
                k.mm(psY[:, blk * 32:(blk + 1) * 32], Hr[b][hc][:, nb, :], Cr[:, blk, :], True, False, [K_('Hr', hc), 'Cr'], ['psY'])
                k.mm(psY[:, blk * 32:(blk + 1) * 32], Hi[b][hc][:, nb, :], nCi[:, blk, :], False, True, [K_('Hi', hc), 'nCi'], ['psY'])
        k.tt('pool', yo[b][:], ut[b4][:], dbc[:], ALU.mult, [f'ut{b4}', 'dbc'], [f'yo{b}'])
        k.tt('dve', yo[b][:], yo[b][:], psY[:, 0:256], ALU.add, [f'yo{b}', 'psY'], [f'yo{b}'])
        k.dma('pool', y[rows, :], yo[b][:], r=[f'yo{b}'], final=True)

    yield from pipeline_gen(tile, NT)


def build_S5(L, k=None):
    k = k or K()
    for _ in gen_S5(L, k):
        pass
    return k.finish()


def s5_host_inputs(s, proj_u, prm):
    gs = slice(16 * s, 16 * s + 16)
    cs = slice(256 * s, 256 * s + 256)
    uc = np.ascontiguousarray(proj_u[:, cs])
    Bre = np.zeros((2, 128, 512), np.float32)
    Bim = np.zeros((2, 128, 512), np.float32)
    Cre = np.zeros((8, 128, 32), np.float32)
    Cim = np.zeros((8, 128, 32), np.float32)
    b_re, b_im = prm['s5_b_re'][gs], prm['s5_b_im'][gs]
    c_re, c_im = prm['s5_c_re'][gs], prm['s5_c_im'][gs]
    for g in range(16):
        hc, gl = g // 8, g % 8
        Bre[hc, gl * 16:(gl + 1) * 16, gl * 64:(gl + 1) * 64] = b_re[g].T
        Bim[hc, gl * 16:(gl + 1) * 16, gl * 64:(gl + 1) * 64] = b_im[g].T
        blk, g2 = g // 2, g % 2
        Cre[blk, g2 * 64:(g2 + 1) * 64, g2 * 16:(g2 + 1) * 16] = c_re[g].T
        Cim[blk, g2 * 64:(g2 + 1) * 64, g2 * 16:(g2 + 1) * 16] = c_im[g].T
    return dict(uT=np.ascontiguousarray(uc.T), u=uc,
                lam_re=np.ascontiguousarray(prm['s5_lambda_re'][gs].reshape(-1)),
                lam_im=np.ascontiguousarray(prm['s5_lambda_im'][gs].reshape(-1)),
                lstep=np.ascontiguousarray(np.repeat(prm['s5_log_step'][gs], 64)),
                Bre=Bre, Bim=Bim, Cre=Cre, Cim=Cim, dsk=np.ascontiguousarray(prm['s5_d'][cs]),
                triu=np.triu(np.ones((128, 128), np.float32)),
                iota_p=np.arange(128, dtype=np.float32).reshape(128, 1),
                iota_f=np.tile(np.arange(128, dtype=np.float32)[None], (128, 1)))


GELU_C = 1.5957691216057308


def gen_LRU(L, k):
    TT = 512
    NCH = L // TT
    xbT = k.din("xbT", [256, L])
    gateT = k.din("gateT", [256, L])
    cw_d = k.din("cw", [128, 2, 4])
    cb_d = k.din("cb", [128, 2])
    Wa_d = k.din("Wa", [2, 128, 128])
    Wx_d = k.din("Wx", [2, 128, 128])
    ba_d = k.din("ba", [128, 2])
    bx_d = k.din("bx", [128, 2])
    lam_d = k.din("lam", [128, 2])
    odT = k.dout("odT", [256, L])
    cw = k.sb("cw_s", [128, 2, 4])
    cb = k.sb("cb_s", [128, 2])
    Wa = k.sb("Wa_s", [128, 2, 128])
    Wx = k.sb("Wx_s", [128, 2, 128])
    ba = k.sb("ba_s", [128, 2])
    bx = k.sb("bx_s", [128, 2])
    c8 = k.sb("c8", [128, 2])
    k.dma('sp', cw[:], cw_d, w=['cw'])
    k.dma('sp', cb[:], cb_d, w=['cb'])
    k.dma('sp', Wa[:], Wa_d.rearrange("b p n -> p b n"), w=['Wa'])
    k.dma('sp', Wx[:], Wx_d.rearrange("b p n -> p b n"), w=['Wx'])
    k.dma('sp', ba[:], ba_d, w=['ba'])
    k.dma('sp', bx[:], bx_d, w=['bx'])
    k.dma('sp', c8[:], lam_d, w=['c8'])
    k.act(c8[:], c8[:], AF.Exp, ['c8'], ['c8'], scale=-1.0)
    k.act(c8[:], c8[:], AF.Ln, ['c8'], ['c8'], bias=1.0)
    k.ts('dve', c8[:], c8[:], -8.0, None, ALU.mult, None, ['c8'], ['c8'])
    hlast = k.sb("hlast", [128, 2])
    k.memset('dve', hlast[:], 0.0, ['hlast0', 'hlast1'])
    xh = [k.sb(f"xh{i}", [128, TT + 3]) for i in range(2)]
    gt = [k.sb(f"gt{i}", [128, TT]) for i in range(2)]
    xc = k.sb("xc", [128, TT])
    r = k.sb("r", [128, TT])
    ig = k.sb("ig", [128, TT])
    a = k.sb("a", [128, TT])
    a2 = k.sb("a2", [128, TT])
    bt = k.sb("bt", [128, TT])
    h = k.sb("h", [128, TT])
    g2 = k.sb("g2", [128, TT])
    ge = k.sb("ge", [128, TT])
    ot = [k.sb(f"ot{i}", [128, TT]) for i in range(2)]
    psR = k.ps("psR", [128, TT])
    psI = k.ps("psI", [128, TT])
    n = 0
    for c in range(NCH):
        for pb in range(2):
            b = n % 2
            n += 1
            prow = slice(pb * 128, (pb + 1) * 128)
            if c == 0:
                k.memset('pool', xh[b][:, 0:3], 0.0, [f'xh{b}h'])
                k.dma('sp', xh[b][:, 3:TT + 3], xbT[prow, 0:TT], w=[f'xh{b}'])
            else:
                k.dma('sp', xh[b][:, 0:TT + 3], xbT[prow, c * TT - 3:(c + 1) * TT], w=[f'xh{b}', f'xh{b}h'])
            k.dma('sp', gt[b][:], gateT[prow, c * TT:(c + 1) * TT], w=[f'gt{b}'])
            xk = [f'xh{b}', f'xh{b}h']
            k.ts('dve', xc[:], xh[b][:, 3:TT + 3], cw[:, pb, 3:4], cb[:, pb:pb + 1], ALU.mult, ALU.add, xk + ['cw', 'cb'], ['xc'])
            for j in (2, 1, 0):
                k.stt(xc[:], xh[b][:, j:j + TT], cw[:, pb, j:j + 1], xc[:], ALU.mult, ALU.add, xk + ['cw', 'xc'], ['xc'])
            k.mm(psR[:], Wa[:, pb, :], xc[:], True, True, ['Wa', 'xc'], ['psR'])
            k.mm(psI[:], Wx[:, pb, :], xc[:], True, True, ['Wx', 'xc'], ['psI'])
            k.act(r[:], psR[:], AF.Sigmoid, ['psR', 'ba'], ['r'], bias=ba[:, pb:pb + 1])
            k.act(ig[:], psI[:], AF.Sigmoid, ['psI', 'bx'], ['ig'], bias=bx[:, pb:pb + 1])
            k.act(a[:], r[:], AF.Exp, ['r', 'c8'], ['a'], scale=c8[:, pb:pb + 1])
            k.act(a2[:], a[:], AF.Square, ['a'], ['a2'])
            k.act(a2[:], a2[:], AF.Sqrt, ['a2'], ['a2'], scale=-1.0, bias=1.0)
            k.tt('pool', bt[:], ig[:], xc[:], ALU.mult, ['ig', 'xc'], ['bt'])
            k.tt('pool', bt[:], bt[:], a2[:], ALU.mult, ['bt', 'a2'], ['bt'])
            k.P.op('dve', lambda e, pb=pb: e.tensor_tensor_scan(out=h[:], data0=a[:], data1=bt[:], initial=hlast[:, pb:pb + 1],
                                                                op0=ALU.mult, op1=ALU.add),
                   reads=['a', 'bt', f'hlast{pb}'], writes=['h'])
            k.cp('dve', hlast[:, pb:pb + 1], h[:, TT - 1:TT], ['h'], [f'hlast{pb}'])
            k.act(g2[:], gt[b][:], AF.Square, [f'gt{b}'], ['g2'])
            k.act(g2[:], g2[:], AF.Copy, ['g2'], ['g2'], scale=0.044715, bias=1.0)
            k.tt('pool', g2[:], g2[:], gt[b][:], ALU.mult, ['g2', f'gt{b}'], ['g2'])
            k.act(g2[:], g2[:], AF.Sigmoid, ['g2'], ['g2'], scale=GELU_C)
            k.tt('pool', ge[:], g2[:], gt[b][:], ALU.mult, ['g2', f'gt{b}'], ['ge'])
            k.tt('dve', ot[b][:], h[:], ge[:], ALU.mult, ['h', 'ge'], [f'ot{b}'])
            k.dma('pool', odT[prow, c * TT:(c + 1) * TT], ot[b][:], r=[f'ot{b}'], final=True)
            yield


def build_LRU(L, k=None):
    k = k or K()
    for _ in gen_LRU(L, k):
        pass
    return k.finish()


def lru_host_inputs(s, xb, gate, prm):
    cs = slice(256 * s, 256 * s + 256)
    col = lambda v: np.ascontiguousarray(v[cs].reshape(2, 128).T)
    Wa = np.zeros((2, 128, 128), np.float32)
    Wx = np.zeros((2, 128, 128), np.float32)
    for pb in range(2):
        for bl in range(2):
            blk = 4 * s + 2 * pb + bl
            Wa[pb, bl * 64:(bl + 1) * 64, bl * 64:(bl + 1) * 64] = prm['lru_w_a'][blk]
            Wx[pb, bl * 64:(bl + 1) * 64, bl * 64:(bl + 1) * 64] = prm['lru_w_x'][blk]
    cw = np.ascontiguousarray(prm['lru_conv_w'][:, cs].reshape(4, 2, 128).transpose(2, 1, 0))
    return dict(xbT=np.ascontiguousarray(xb[:, cs].T), gateT=np.ascontiguousarray(gate[:, cs].T), cw=cw,
                cb=col(prm['lru_conv_b']), Wa=Wa, Wx=Wx, ba=col(prm['lru_b_a']), bx=col(prm['lru_b_x']),
                lam=col(prm['lru_lambda']))


GN_EPS = 64e-5
NLEV = 5


def build_RWKV(L, k=None, NH=4, fr=False, CH=64):
    k = k or K()
    NT = L // 128
    W = NH * 64
    NG = NH // 4
    FR = mybir.dt.float32r if fr else F32
    rd = (lambda ap: ap.bitcast(F32)) if fr else (lambda ap: ap)
    NCK = 128 // CH
    nlev = 5 if CH == 64 else 6
    frc = fr and CH == 128
    FRC = mybir.dt.float32r if frc else F32
    rdc = (lambda ap: ap.bitcast(F32)) if frc else (lambda ap: ap)
    lhc = (lambda ap: ap) if frc else rd
    prkv = [k.din(nm, [L, W]) for nm in ("pr", "pk", "pv")]
    mu1 = k.din("mu1", [3 * W])
    pls = [k.din("plw", [64, L]), k.din("pla", [64, L]), k.din("plg", [128, L])]
    mul = k.din("mul", [128, 3])
    w2 = k.din("w2", [64, W])
    a2 = k.din("a2", [64, W])
    g2 = k.din("g2", [128, W])
    vecs = k.din("vecs", [7, W])
    ident_d = k.din("ident", [128, 128])
    triw_d = k.din("triw", [3, 128, 128])
    mask5_d = k.din("mask5", [128, 640])
    rowm_d = k.din("rowm", [128, 2])
    oc = k.dout("oc", [L, W])

    k.consts(ident_d)
    triw = k.sb("triw_s", [128, 3, 128])
    k.dma('sp', triw[:], triw_d.rearrange("a p n -> p a n"), w=['triw'])
    mask5 = k.sb("mask5_s", [128, 640])
    k.dma('sp', mask5[:], mask5_d, w=['mask5'])
    rowm = k.sb("rowm_s", [128, 2])
    k.dma('sp', rowm[:], rowm_d, w=['rowm'])
    mu1bc = k.bcast_row("mu1bc", mu1, 3 * W)
    vb = [k.bcast_row(f"vb{i}", vecs[i], W) for i in range(7)]
    w0bc, a0bc, kkbc, kabc, rkbc, lngbc, lnbbc = vb
    VK = [f"vb{i}" for i in range(7)]
    muls = k.sb("muls", [128, 3])
    k.dma('sp', muls[:], mul, w=['muls'])
    w2s = k.sb("w2s", [64, W])
    a2s = k.sb("a2s", [64, W])
    k.dma('sp', w2s[:], w2, w=['w2s'])
    k.dma('sp', a2s[:], a2, w=['a2s'])
    g2s = k.sb("g2s", [128, W])
    k.dma('sp', g2s[:], g2, w=['g2s'])
    ST = [k.sb(f"ST{i}", [64, 64], FRC) for i in range(NH)]
    zt = k.sb("zt", [128, W])
    k.memset('dve', zt[:], 0.0, ['zt'])
    for i in range(NH):
        k.cp('dve', ST[i][:], zt[0:64, 0:64], ['zt'], [f'ST{i}'])
    P1s = k.sb("P1s", [128, W], FRC)
    Us = k.sb("Us", [128, W], FRC)
    k.cp('dve', P1s[:], zt[:], ['zt'], ['P1s'])
    k.cp('dve', Us[:], zt[:], ['zt'], ['Us'])

    pt = [k.sb(f"pt{i}", [128, 3 * W]) for i in range(2)]
    pp = [k.sb(f"pp{i}", [128, 3 * W]) for i in range(2)]
    lt = [k.sb(f"lt{i}", [128, 3, 128]) for i in range(2)]
    lp = [k.sb(f"lp{i}", [128, 3, 128]) for i in range(2)]
    for i_ in range(2):
        k.memset('pool', lt[i_][:], 0.0, [f'lt{i_}0', f'lt{i_}1', f'lt{i_}2'])
        k.memset('pool', lp[i_][:], 0.0, [f'lp{i_}0', f'lp{i_}1', f'lp{i_}2', f'lp{i_}z'])
    pm = k.sb("pm", [128, 3 * W])
    vr = k.sb("vr", [128, W], FR)
    lm = k.sb("lm", [128, 3, 128])
    sw = k.sb("sw", [128, W])
    av = k.sb("av", [128, W])
    gv = k.sb("gv", [128, W])
    kkr = k.sb("kkr", [128, W])
    sq = k.sb("sq", [128, W])
    s4 = k.sb("s4", [128, NH])
    rn = k.sb("rn", [128, NH])
    nkk = k.sb("nkk", [128, W])
    kmod = k.sb("kmod", [128, W])
    kka = k.sb("kka", [128, W])
    tmp = k.sb("tmp", [128, W])
    bon = k.sb("bon", [128, NH])
    E1 = k.sb("E1", [128, W])
    E2 = k.sb("E2", [128, W])
    E3 = k.sb("E3", [128, W])
    E4 = k.sb("E4", [128, W])
    E1T = k.sb("E1T", [64, NH, 128])
    At = k.sb("At", [128, W])
    Bs = k.sb("Bs", [128, W])
    Ks = k.sb("Ks", [128, W])
    Rt = k.sb("Rt", [128, W])
    Bfm = [k.sb(f"Bfm{c}", [128, W]) for c in range(2)]
    Kfm = [k.sb(f"Kfm{c}", [128, W]) for c in range(2)]
    FT = [k.sb(f"FT{h}", [64, 4, 128], FR) for h in range(NH)]
    A5 = [k.sb(f"A5_{h}", [128, 640], FR) for h in range(NH)]
    NL = [k.sb(f"NL_{h}", [128, 256], FR) for h in range(NH)]
    PQ = [k.sb(f"PQ_{h}", [128, 256], FR) for h in range(NH)]
    W1 = k.sb("W1", [128, W], FR)
    U1 = k.sb("U1", [128, W])
    ysb = k.sb("ysb", [128, W])
    yc = k.sb("yc", [128, W])
    m4 = k.sb("m4", [128, NH])
    r4 = k.sb("r4", [128, NH])
    ot = [k.sb(f"ot{i}", [128, W]) for i in range(2)]
    B = [k.ps(f"psB{i}", [128, 512]) for i in range(8)]
    bk = lambda i: f'psB{i}'
    v3 = lambda t: t.rearrange("p (h j) -> p h j", h=NH)
    bc4 = lambda t: t.unsqueeze(2).broadcast_to([128, NH, 64])

    for i in range(NT):
        b = i % 2
        rows = slice(i * 128, (i + 1) * 128)
        PK, PPK, LTK, LPK = [], [], [], []
        for q in range(3):
            cq = slice(q * W, (q + 1) * W)
            k.dma('sp', pt[b][:, cq], prkv[q][rows, :], w=[f'pt{b}{q}'])
            PK.append(f'pt{b}{q}')
            if i == 0:
                k.dma('sp', pp[b][1:128, cq], prkv[q][0:127, :], w=[f'pp{b}{q}'])
            else:
                k.dma('sp', pp[b][:, cq], prkv[q][i * 128 - 1:i * 128 + 127, :], w=[f'pp{b}{q}'])
            PPK.append(f'pp{b}{q}')
            nr = pls[q].shape[0]
            k.dma('sp', lt[b][0:nr, q, :], pls[q][:, rows], w=[f'lt{b}{q}'])
            LTK.append(f'lt{b}{q}')
            if i == 0:
                k.dma('sp', lp[b][0:nr, q, 1:128], pls[q][:, 0:127], w=[f'lp{b}{q}'])
            else:
                k.dma('sp', lp[b][0:nr, q, :], pls[q][:, i * 128 - 1:i * 128 + 127], w=[f'lp{b}{q}'])
            LPK.append(f'lp{b}{q}')
        if i == 0:
            k.memset('pool', pp[b][0:1, :], 0.0, [f'pp{b}z'])
            k.memset('pool', lp[b][:, :, 0:1], 0.0, [f'lp{b}z'])
            PPK.append(f'pp{b}z')
            LPK.append(f'lp{b}z')
        k.tt('pool', pm[:], pp[b][:], pt[b][:], ALU.subtract, PPK + PK, ['pm'])
        k.tt('pool', pm[:], pm[:], mu1bc[:], ALU.mult, ['pm', 'mu1bc'], ['pm'])
        k.tt('pool', pm[:], pm[:], pt[b][:], ALU.add, ['pm'] + PK, ['pm'])
        r_, k_, v_ = pm[:, 0:W], pm[:, W:2 * W], pm[:, 2 * W:3 * W]
        k.cp('act', vr[:], v_, ['pm'], ['vr'])
        LK = LTK + LPK
        k.tt('dve', lm[:], lp[b][:], lt[b][:], ALU.subtract, LK, ['lm'])
        for blk in range(3):
            k.stt(lm[:, blk, :], lm[:, blk, :], muls[:, blk:blk + 1], lt[b][:, blk, :], ALU.mult, ALU.add,
                  ['lm', 'muls'] + LK, ['lm'])
        k.act(lm[0:64, 0, :], lm[0:64, 0, :], AF.Tanh, ['lm'], ['lm'])
        k.act(lm[:, 2, :], lm[:, 2, :], AF.Sigmoid, ['lm'], ['lm'])
        k.mm(B[0][:, 0:W], lm[0:64, 0, :], w2s[:], True, True, ['lm', 'w2s'], [bk(0)])
        k.mm(B[1][:, 0:W], lm[0:64, 1, :], a2s[:], True, True, ['lm', 'a2s'], [bk(1)])
        k.mm(B[2][:, 0:W], lm[:, 2, :], g2s[:], True, True, ['lm', 'g2s'], [bk(2)])
        k.tt('dve', sw[:], B[0][:, 0:W], w0bc[:], ALU.add, [bk(0), VK[0]], ['sw'])
        k.act(sw[:], sw[:], AF.Sigmoid, ['sw'], ['sw'])
        k.tt('dve', av[:], B[1][:, 0:W], a0bc[:], ALU.add, [bk(1), VK[1]], ['av'])
        k.act(av[:], av[:], AF.Sigmoid, ['av'], ['av'])
        k.cp('act', gv[:], B[2][:, 0:W], [bk(2)], ['gv'])
        k.tt('pool', kkr[:], k_, kkbc[:], ALU.mult, ['pm', VK[2]], ['kkr'])
        k.tt('pool', sq[:], kkr[:], kkr[:], ALU.mult, ['kkr'], ['sq'])
        k.P.op('dve', lambda e: e.tensor_reduce(out=s4[:], in_=v3(sq[:]), axis=AX.X, op=ALU.add), reads=['sq'], writes=['s4'])
        k.act(s4[:], s4[:], AF.Sqrt, ['s4'], ['s4'])
        k.ts('dve', s4[:], s4[:], 1e-12, None, ALU.max, None, ['s4'], ['s4'])
        k.recip(rn[:], s4[:], ['s4'], ['rn'])
        k.ts('dve', rn[:], rn[:], -1.0, None, ALU.mult, None, ['rn'], ['rn'])
        k.tt('dve', v3(nkk[:]), v3(kkr[:]), bc4(rn[:]), ALU.mult, ['kkr', 'rn'], ['nkk'])
        k.stt(tmp[:], av[:], -1.0, kabc[:], ALU.add, ALU.mult, ['av', VK[3]], ['tmp'])
        k.stt(kmod[:], tmp[:], 1.0, k_, ALU.add, ALU.mult, ['tmp', 'pm'], ['kmod'])
        k.stt(kka[:], nkk[:], -1.0, av[:], ALU.mult, ALU.mult, ['nkk', 'av'], ['kka'])
        k.tt('pool', tmp[:], r_, kmod[:], ALU.mult, ['pm', 'kmod', 'tmp'], ['tmp'])
        k.tt('pool', tmp[:], tmp[:], rkbc[:], ALU.mult, ['tmp', VK[4]], ['tmp'])
        k.P.op('dve', lambda e: e.tensor_reduce(out=bon[:], in_=v3(tmp[:]), axis=AX.X, op=ALU.add), reads=['tmp'], writes=['bon'])
        k.mm(B[3][:, 0:W], triw[:, 0, :], sw[:], True, True, ['triw', 'sw'], [bk(3)])
        k.mm(B[4][:, 0:W], triw[:, 1, :], sw[:], True, True, ['triw', 'sw'], [bk(4)])
        k.mm(B[5][:, 0:W], triw[:, 2, :], sw[:], True, True, ['triw', 'sw'], [bk(5)])
        for h in range(NH):
            k.mm(B[6 + h // 4][0:64, (h % 4) * 128:(h % 4 + 1) * 128], sw[:, h * 64:(h + 1) * 64], triw[:, 0, :], True, True,
                 ['sw', 'triw'], [bk(6 + h // 4)])
        k.act(E1[:], B[3][:, 0:W], AF.Exp, [bk(3)], ['E1'])
        k.act(E2[:], B[3][:, 0:W], AF.Exp, [bk(3)], ['E2'], scale=-1.0)
        k.act(E3[:], B[4][:, 0:W], AF.Exp, [bk(4)], ['E3'])
        k.act(E4[:], B[5][:, 0:W], AF.Exp, [bk(5)], ['E4'])
        for g in range(NG):
            k.act(E1T[:, 4 * g:4 * g + 4, :].rearrange("p a t -> p (a t)"), B[6 + g][0:64, :], AF.Exp, [bk(6 + g)], ['E1T'])
        k.tt('dve', At[:], nkk[:], E3[:], ALU.mult, ['nkk', 'E3'], ['At'])
        k.tt('pool', Bs[:], kka[:], E2[:], ALU.mult, ['kka', 'E2'], ['Bs'])
        k.tt('dve', Ks[:], kmod[:], E2[:], ALU.mult, ['kmod', 'E2'], ['Ks'])
        k.tt('pool', Rt[:], r_, E1[:], ALU.mult, ['pm', 'E1'], ['Rt'])
        for c in range(NCK):
            k.stt(Bfm[c][:], kka[:], rowm[:, c:c + 1], E4[:], ALU.mult, ALU.mult, ['kka', 'E4', 'rowm'], [f'Bfm{c}'])
            k.stt(Kfm[c][:], kmod[:], rowm[:, c:c + 1], E4[:], ALU.mult, ALU.mult, ['kmod', 'E4', 'rowm'], [f'Kfm{c}'])
        HS = list(range(NH))
        for h in HS:
            cs_ = slice(h * 64, (h + 1) * 64)
            for q, (src, key) in enumerate([(At, 'At'), (Bs, 'Bs'), (Ks, 'Ks'), (Rt, 'Rt')]):
                k.tr(B[h][0:64, q * 128:(q + 1) * 128], src[:, cs_], k.identf[:], [key], [bk(h)])
        for h in HS:
            k.cp('act' if h % 2 else 'dve', FT[h][:].rearrange("p a t -> p (a t)"), B[h][0:64, :], [bk(h)], [f'FT{h}'])
        for h in HS:
            AtT, BsT, KsT, RtT = (FT[h][:, q, :] for q in range(4))
            o = lambda j: B[h][:, j * 128:(j + 1) * 128]
            k.mm(o(0), BsT, AtT, True, True, [f'FT{h}'], [bk(h)])
            k.mm(o(1), AtT, BsT, True, True, [f'FT{h}'], [bk(h)])
            k.mm(o(2), KsT, AtT, True, True, [f'FT{h}'], [bk(h)])
        for h in HS:
            k.tt('dve', A5[h][:, 0:384], B[h][:, 0:384], mask5[:, 0:384], ALU.mult, [bk(h), 'mask5'], [f'A5_{h}'])
        for h in HS:
            AtT, BsT, KsT, RtT = (FT[h][:, q, :] for q in range(4))
            k.mm(B[h][:, 0:128], BsT, RtT, True, True, [f'FT{h}'], [bk(h)])
            k.mm(B[h][:, 128:256], KsT, RtT, True, True, [f'FT{h}'], [bk(h)])
        for h in HS:
            k.tt('dve', A5[h][:, 384:640], B[h][:, 0:256], mask5[:, 384:640], ALU.mult, [bk(h), 'mask5'], [f'A5b_{h}'])
            k.cp('act', NL[h][:], rd(A5[h][:, 0:256]), [f'A5_{h}'], [f'NL_{h}'])
            k.tt('pool' if not fr else 'dve', PQ[h][:].rearrange("p (a n) -> p a n", a=2), rd(A5[h][:, 0:256]).rearrange("p (a n) -> p a n", a=2),
                 k.identf[:].unsqueeze(1).broadcast_to([128, 2, 128]), ALU.add, [f'A5_{h}', 'ident'], [f'PQ_{h}'])
        for lev in range(nlev):
            for h in HS:
                N_, L_ = NL[h][:, 0:128], NL[h][:, 128:256]
                k.mm(B[h][:, 0:128], L_, N_, True, True, [f'NL_{h}'], [bk(h)])
                k.mm(B[h][:, 128:256], N_, L_, True, True, [f'NL_{h}'], [bk(h)])
            for h in HS:
                k.cp('act', NL[h][:], B[h][:, 0:256], [bk(h)], [f'NL_{h}'])
            for h in HS:
                N_, L_ = NL[h][:, 0:128], NL[h][:, 128:256]
                P_, Q_ = PQ[h][:, 0:128], PQ[h][:, 128:256]
                k.mm(B[h][:, 256:384], Q_, N_, True, True, [f'NL_{h}', f'PQ_{h}'], [bk(h)])
                k.mm(B[h][:, 384:512], P_, L_, True, True, [f'NL_{h}', f'PQ_{h}'], [bk(h)])
            for h in HS:
                k.tt('dve', PQ[h][:], B[h][:, 256:512], rd(PQ[h][:]), ALU.add, [bk(h), f'PQ_{h}'], [f'PQ_{h}'])
        for h in range(NH):
            k.mm(B[0][:, h * 64:(h + 1) * 64], A5[h][:, 256:384], vr[:, h * 64:(h + 1) * 64], True, True, [f'A5_{h}', 'vr'], [bk(0)])
        k.cp('act', W1[:], B[0][:, 0:W], [bk(0)], ['W1'])
        for h in range(NH):
            k.mm(B[1][:, h * 64:(h + 1) * 64], PQ[h][:, 0:128], W1[:, h * 64:(h + 1) * 64], True, True,
                 [f'PQ_{h}', 'W1'], [bk(1)])
        k.cp('act', U1[:], B[1][:, 0:W], [bk(1)], ['U1'])
        vsrc = vr if frc else None
        for c in range(NCK):
            cr = slice(c * CH, (c + 1) * CH)
            for h in range(NH):
                k.mm(B[2][cr, h * 64:(h + 1) * 64], lhc(FT[h][:, 0, cr]), ST[h][:], True, True, [f'FT{h}', f'ST{h}'], [bk(2)])
            k.cp('act', P1s[cr, :], B[2][cr, 0:W], [bk(2)], ['P1s'])
            for h in range(NH):
                k.mm(B[3][cr, h * 64:(h + 1) * 64], lhc(PQ[h][:, cr]), P1s[:, h * 64:(h + 1) * 64], True, True,
                     [f'PQ_{h}', 'P1s'], [bk(3)])
            k.tt('dve', Us[cr, :], B[3][cr, 0:W], U1[cr, :], ALU.add, [bk(3), 'U1'], ['Us'])
            for h in range(NH):
                hc_ = slice(h * 64, (h + 1) * 64)
                vh = vr[:, hc_] if frc else pm[:, 2 * W + h * 64:2 * W + (h + 1) * 64]
                vk = 'vr' if frc else 'pm'
                k.mm(B[6][cr, hc_], lhc(FT[h][:, 3, cr]), ST[h][:], True, False, [f'FT{h}', f'ST{h}'], [bk(6)])
                k.mm(B[6][cr, hc_], lhc(A5[h][:, 384:512][:, cr]), Us[:, hc_], False, False, [f'A5b_{h}', 'Us'], [bk(6)])
                k.mm(B[6][cr, hc_], lhc(A5[h][:, 512:640][:, cr]), vh, False, True, [f'A5b_{h}', vk], [bk(6)])
            for h in range(NH):
                hc_ = slice(h * 64, (h + 1) * 64)
                vh = pm[:, 2 * W + h * 64:2 * W + (h + 1) * 64]
                k.mm(B[7][0:64, hc_], Bfm[c][:, hc_], rdc(Us[:, hc_]), True, False, [f'Bfm{c}', 'Us'], [bk(7)])
                k.mm(B[7][0:64, hc_], Kfm[c][:, hc_], vh, False, True, [f'Kfm{c}', 'pm'], [bk(7)])
            for h in range(NH):
                hc_ = slice(h * 64, (h + 1) * 64)
                k.stt(ST[h][:], rdc(ST[h][:]), E1T[:, h, (c + 1) * CH - 1:(c + 1) * CH], B[7][0:64, hc_], ALU.mult, ALU.add,
                      [f'ST{h}', 'E1T', bk(7)], [f'ST{h}'])
        k.cp('act', ysb[:], B[6][:, 0:W], [bk(6)], ['ysb'])
        k.P.op('dve', lambda e: e.tensor_reduce(out=m4[:], in_=v3(ysb[:]), axis=AX.X, op=ALU.add), reads=['ysb'], writes=['m4'])
        k.ts('dve', m4[:], m4[:], -1.0 / 64.0, None, ALU.mult, None, ['m4'], ['m4'])
        k.tt('dve', v3(yc[:]), v3(ysb[:]), bc4(m4[:]), ALU.add, ['ysb', 'm4'], ['yc'])
        k.tt('pool', sq[:], yc[:], yc[:], ALU.mult, ['yc'], ['sq'])
        k.P.op('dve', lambda e: e.tensor_reduce(out=r4[:], in_=v3(sq[:]), axis=AX.X, op=ALU.add), reads=['sq'], writes=['r4'])
        k.ts('dve', r4[:], r4[:], 1.0 / 64.0, GN_EPS, ALU.mult, ALU.add, ['r4'], ['r4'])
        k.act(r4[:], r4[:], AF.Sqrt, ['r4'], ['r4'])
        k.recip(r4[:], r4[:], ['r4'], ['r4'])
        k.tt('dve', v3(yc[:]), v3(yc[:]), bc4(r4[:]), ALU.mult, ['yc', 'r4'], ['yc'])
        k.tt('pool', yc[:], yc[:], lngbc[:], ALU.mult, ['yc', VK[5]], ['yc'])
        k.tt('pool', yc[:], yc[:], lnbbc[:], ALU.add, ['yc', VK[6]], ['yc'])
        k.tt('dve', v3(tmp[:]), v3(v_), bc4(bon[:]), ALU.mult, ['pm', 'bon', 'tmp'], ['tmp'])
        k.tt('pool', yc[:], yc[:], tmp[:], ALU.add, ['yc', 'tmp'], ['yc'])
        k.tt('dve', ot[b][:], yc[:], gv[:], ALU.mult, ['yc', 'gv'], [f'ot{b}'])
        k.dma('pool', oc[rows, :], ot[b][:], r=[f'ot{b}'], final=True)
    return k.finish()


def build_RWKVP(L, k=None, CH=64):
    NH, fr = 8, True
    k = k or K()
    NT = L // 128
    W = NH * 64
    NG = NH // 4
    FR = mybir.dt.float32r if fr else F32
    rd = (lambda ap: ap.bitcast(F32)) if fr else (lambda ap: ap)
    NCK = 128 // CH
    nlev = 5 if CH == 64 else 6
    frc = fr and CH == 128
    FRC = mybir.dt.float32r if frc else F32
    rdc = (lambda ap: ap.bitcast(F32)) if frc else (lambda ap: ap)
    lhc = (lambda ap: ap) if frc else rd
    prkv = [k.din(nm, [L, W]) for nm in ("pr", "pk", "pv")]
    mu1 = k.din("mu1", [3 * W])
    pls = [k.din("plw", [64, L]), k.din("pla", [64, L]), k.din("plg", [128, L])]
    mul = k.din("mul", [128, 3])
    w2 = k.din("w2", [64, W])
    a2 = k.din("a2", [64, W])
    g2 = k.din("g2", [128, W])
    vecs = k.din("vecs", [7, W])
    ident_d = k.din("ident", [128, 128])
    triw_d = k.din("triw", [3, 128, 128])
    mask5_d = k.din("mask5", [128, 640])
    rowm_d = k.din("rowm", [128, 2])
    oc = k.dout("oc", [L, W])

    k.consts(ident_d)
    triw = k.sb("triw_s", [128, 3, 128])
    k.dma('sp', triw[:], triw_d.rearrange("a p n -> p a n"), w=['triw'])
    mask5 = k.sb("mask5_s", [128, 640])
    k.dma('sp', mask5[:], mask5_d, w=['mask5'])
    rowm = k.sb("rowm_s", [128, 2])
    k.dma('sp', rowm[:], rowm_d, w=['rowm'])
    mu1bc = k.bcast_row("mu1bc", mu1, 3 * W)
    vb = [k.bcast_row(f"vb{i}", vecs[i], W) for i in range(7)]
    w0bc, a0bc, kkbc, kabc, rkbc, lngbc, lnbbc = vb
    VK = [f"vb{i}" for i in range(7)]
    muls = k.sb("muls", [128, 3])
    k.dma('sp', muls[:], mul, w=['muls'])
    w2s = k.sb("w2s", [64, W])
    a2s = k.sb("a2s", [64, W])
    k.dma('sp', w2s[:], w2, w=['w2s'])
    k.dma('sp', a2s[:], a2, w=['a2s'])
    g2s = k.sb("g2s", [128, W])
    k.dma('sp', g2s[:], g2, w=['g2s'])
    ST = [k.sb(f"ST{i}", [64, 64], FRC) for i in range(NH)]
    zt = k.sb("zt", [128, W])
    k.memset('dve', zt[:], 0.0, ['zt'])
    for i in range(NH):
        k.cp('dve', ST[i][:], zt[0:64, 0:64], ['zt'], [f'ST{i}'])
    P1s = k.sb("P1s", [128, W], FRC)
    Us = k.sb("Us", [128, W], FRC)
    k.cp('dve', P1s[:], zt[:], ['zt'], ['P1s'])
    k.cp('dve', Us[:], zt[:], ['zt'], ['Us'])

    pt = [k.sb("pt0", [128, 3 * W])] * 2
    pp = [k.sb("pp0", [128, 3 * W])] * 2
    lt = [k.sb("lt0", [128, 3, 128])] * 2
    lp = [k.sb("lp0", [128, 3, 128])] * 2
    k.memset('pool', lt[0][:], 0.0, ['lt0', 'lt1', 'lt2'])
    k.memset('pool', lp[0][:], 0.0, ['lp0', 'lp1', 'lp2', 'lpz'])
    pm2 = [k.sb(f"pm{i_}", [128, 3 * W]) for i_ in range(2)]
    vr2 = [k.sb(f"vr{i_}", [128, W], FR) for i_ in range(2)]
    lm2 = [k.sb(f"lm{i_}", [128, 3, 128]) for i_ in range(2)]
    sw = k.sb("sw", [128, W])
    av = k.sb("av", [128, W])
    gv2 = [k.sb(f"gv{i_}", [128, W]) for i_ in range(2)]
    kkr = k.sb("kkr", [128, W])
    sq = k.sb("sq", [128, W])
    s4 = k.sb("s4", [128, NH])
    rn = k.sb("rn", [128, NH])
    nkk = k.sb("nkk", [128, W])
    kmod = k.sb("kmod", [128, W])
    kka = k.sb("kka", [128, W])
    tmp = k.sb("tmp", [128, W])
    bon2 = [k.sb(f"bon{i_}", [128, NH]) for i_ in range(2)]
    E1 = k.sb("E1", [128, W])
    E2 = k.sb("E2", [128, W])
    E3 = k.sb("E3", [128, W])
    E4 = k.sb("E4", [128, W])
    E1T2 = [k.sb(f"E1T{i_}", [64, NH, 128]) for i_ in range(2)]
    At2 = [k.sb(f"At{i_}", [128, W]) for i_ in range(2)]
    Bs2 = [k.sb(f"Bs{i_}", [128, W]) for i_ in range(2)]
    Ks2 = [k.sb(f"Ks{i_}", [128, W]) for i_ in range(2)]
    Rt2 = [k.sb(f"Rt{i_}", [128, W]) for i_ in range(2)]
    Bfm2 = [[k.sb(f"Bfm{p_}{c}", [128, W]) for c in range(NCK)] for p_ in range(2)]
    Kfm2 = [[k.sb(f"Kfm{p_}{c}", [128, W]) for c in range(NCK)] for p_ in range(2)]
    sqp = k.sb("sqp", [128, W])
    tmpp = k.sb("tmpp", [128, W])
    FT = [k.sb(f"FT{h}", [64, 4, 128], FR) for h in range(NH)]
    A5 = [k.sb(f"A5_{h}", [128, 640], FR) for h in range(NH)]
    NL = [k.sb(f"NL_{h}", [128, 256], FR) for h in range(NH)]
    PQ = [k.sb(f"PQ_{h}", [128, 128], FR) for h in range(NH)]
    W1 = k.sb("W1", [128, W], FR)
    U1 = k.sb("U1", [128, W])
    ysb = k.sb("ysb", [128, W])
    yc = k.sb("yc", [128, W])
    m4 = k.sb("m4", [128, NH])
    r4 = k.sb("r4", [128, NH])
    ot = [k.sb(f"ot{i}", [128, W]) for i in range(2)]
    B = [k.ps(f"psB{i}", [128, 512]) for i in range(8)]
    bk = lambda i: f'psB{i}'
    v3 = lambda t: t.rearrange("p (h j) -> p h j", h=NH)
    bc4 = lambda t: t.unsqueeze(2).broadcast_to([128, NH, 64])


    S0, S1, C0, C1 = 6, 7, 4, 5

    def tile(i):
        b = i % 2
        pm, lm = pm2[b], lm2[b]
        kpm, klm = f'pm{b}', f'lm{b}'
        At, Bs, Ks, Rt, gv, vr, bon, E1T, Bf, Kf = At2[b], Bs2[b], Ks2[b], Rt2[b], gv2[b], vr2[b], bon2[b], E1T2[b], Bfm2[b], Kfm2[b]
        kAt, kBs, kKs, kRt, kgv, kvr, kbon, kE1T, kBf, kKf = (f'{n_}{b}' for n_ in ('At', 'Bs', 'Ks', 'Rt', 'gv', 'vr', 'bon', 'E1T', 'Bf', 'Kf'))
        rows = slice(i * 128, (i + 1) * 128)
        PK, PPK, LTK, LPK = [], [], [], []
        for q in range(3):
            cq = slice(q * W, (q + 1) * W)
            k.dma('sp', pt[b][:, cq], prkv[q][rows, :], w=[f'pt{q}'])
            PK.append(f'pt{q}')
            if i == 0:
                k.dma('sp', pp[b][1:128, cq], prkv[q][0:127, :], w=[f'pp{q}'])
            else:
                k.dma('sp', pp[b][:, cq], prkv[q][i * 128 - 1:i * 128 + 127, :], w=[f'pp{q}'])
            PPK.append(f'pp{q}')
            nr = pls[q].shape[0]
            k.dma('sp', lt[b][0:nr, q, :], pls[q][:, rows], w=[f'lt{q}'])
            LTK.append(f'lt{q}')
            if i == 0:
                k.dma('sp', lp[b][0:nr, q, 1:128], pls[q][:, 0:127], w=[f'lp{q}'])
            else:
                k.dma('sp', lp[b][0:nr, q, :], pls[q][:, i * 128 - 1:i * 128 + 127], w=[f'lp{q}'])
            LPK.append(f'lp{q}')
        if i == 0:
            k.memset('pool', pp[b][0:1, :], 0.0, ['ppz'])
            k.memset('pool', lp[b][:, :, 0:1], 0.0, ['lpz'])
            PPK.append('ppz')
            LPK.append('lpz')
        k.tt('pool', pm[:], pp[b][:], pt[b][:], ALU.subtract, PPK + PK, [kpm])
        k.tt('pool', pm[:], pm[:], mu1bc[:], ALU.mult, [kpm, 'mu1bc'], [kpm])
        k.tt('pool', pm[:], pm[:], pt[b][:], ALU.add, [kpm] + PK, [kpm])
        r_, k_, v_ = pm[:, 0:W], pm[:, W:2 * W], pm[:, 2 * W:3 * W]
        LK = LTK + LPK
        k.tt('dve', lm[:], lp[b][:], lt[b][:], ALU.subtract, LK, [klm])
        for blk in range(3):
            k.stt(lm[:, blk, :], lm[:, blk, :], muls[:, blk:blk + 1], lt[b][:, blk, :], ALU.mult, ALU.add,
                  [klm, 'muls'] + LK, [klm])
        k.act(lm[0:64, 0, :], lm[0:64, 0, :], AF.Tanh, [klm], [klm])
        k.act(lm[:, 2, :], lm[:, 2, :], AF.Sigmoid, [klm], [klm])
        yield
        k.cp('act', vr[:], v_, [kpm], [kvr])
        k.mm(B[S0][:, 0:W], lm[0:64, 0, :], w2s[:], True, True, [klm, 'w2s'], [bk(S0)])
        k.mm(B[S1][:, 0:W], lm[0:64, 1, :], a2s[:], True, True, [klm, 'a2s'], [bk(S1)])
        k.tt('dve', sw[:], B[S0][:, 0:W], w0bc[:], ALU.add, [bk(S0), VK[0]], ['sw'])
        k.act(sw[:], sw[:], AF.Sigmoid, ['sw'], ['sw'])
        k.tt('dve', av[:], B[S1][:, 0:W], a0bc[:], ALU.add, [bk(S1), VK[1]], ['av'])
        k.act(av[:], av[:], AF.Sigmoid, ['av'], ['av'])
        k.mm(B[S0][:, 0:W], lm[:, 2, :], g2s[:], True, True, [klm, 'g2s'], [bk(S0)])
        k.cp('act', gv[:], B[S0][:, 0:W], [bk(S0)], [kgv])
        yield
        k.tt('pool', kkr[:], k_, kkbc[:], ALU.mult, [kpm, VK[2]], ['kkr'])
        k.tt('pool', sq[:], kkr[:], kkr[:], ALU.mult, ['kkr'], ['sq'])
        k.P.op('dve', lambda e: e.tensor_reduce(out=s4[:], in_=v3(sq[:]), axis=AX.X, op=ALU.add), reads=['sq'], writes=['s4'])
        k.act(s4[:], s4[:], AF.Sqrt, ['s4'], ['s4'])
        k.ts('dve', s4[:], s4[:], 1e-12, None, ALU.max, None, ['s4'], ['s4'])
        k.recip(rn[:], s4[:], ['s4'], ['rn'])
        k.ts('dve', rn[:], rn[:], -1.0, None, ALU.mult, None, ['rn'], ['rn'])
        k.tt('dve', v3(nkk[:]), v3(kkr[:]), bc4(rn[:]), ALU.mult, ['kkr', 'rn'], ['nkk'])
        k.stt(tmp[:], av[:], -1.0, kabc[:], ALU.add, ALU.mult, ['av', VK[3]], ['tmp'])
        k.stt(kmod[:], tmp[:], 1.0, k_, ALU.add, ALU.mult, ['tmp', kpm], ['kmod'])
        k.stt(kka[:], nkk[:], -1.0, av[:], ALU.mult, ALU.mult, ['nkk', 'av'], ['kka'])
        k.tt('pool', tmp[:], r_, kmod[:], ALU.mult, [kpm, 'kmod', 'tmp'], ['tmp'])
        k.tt('pool', tmp[:], tmp[:], rkbc[:], ALU.mult, ['tmp', VK[4]], ['tmp'])
        k.P.op('dve', lambda e: e.tensor_reduce(out=bon[:], in_=v3(tmp[:]), axis=AX.X, op=ALU.add), reads=['tmp'], writes=[kbon])
        k.mm(B[S1][:, 0:W], triw[:, 0, :], sw[:], True, True, ['triw', 'sw'], [bk(S1)])
        k.mm(B[S0][:, 0:W], triw[:, 1, :], sw[:], True, True, ['triw', 'sw'], [bk(S0)])
        k.act(E1[:], B[S1][:, 0:W], AF.Exp, [bk(S1)], ['E1'])
        k.act(E2[:], B[S1][:, 0:W], AF.Exp, [bk(S1)], ['E2'], scale=-1.0)
        k.act(E3[:], B[S0][:, 0:W], AF.Exp, [bk(S0)], ['E3'])
        k.mm(B[S1][:, 0:W], triw[:, 2, :], sw[:], True, True, ['triw', 'sw'], [bk(S1)])
        k.act(E4[:], B[S1][:, 0:W], AF.Exp, [bk(S1)], ['E4'])
        for g in range(2):
            for hl in range(4):
                h = 4 * g + hl
                k.mm(B[S0 + g][0:64, hl * 128:(hl + 1) * 128], sw[:, h * 64:(h + 1) * 64], triw[:, 0, :], True, True,
                     ['sw', 'triw'], [bk(S0 + g)])
        for g in range(2):
            k.act(E1T[:, 4 * g:4 * g + 4, :].rearrange("p a t -> p (a t)"), B[S0 + g][0:64, :], AF.Exp, [bk(S0 + g)], [kE1T])
        yield
        k.tt('dve', At[:], nkk[:], E3[:], ALU.mult, ['nkk', 'E3'], [kAt])
        k.tt('pool', Bs[:], kka[:], E2[:], ALU.mult, ['kka', 'E2'], [kBs])
        k.tt('dve', Ks[:], kmod[:], E2[:], ALU.mult, ['kmod', 'E2'], [kKs])
        k.tt('pool', Rt[:], r_, E1[:], ALU.mult, [kpm, 'E1'], [kRt])
        for c in range(NCK):
            k.stt(Bf[c][:], kka[:], rowm[:, c:c + 1], E4[:], ALU.mult, ALU.mult, ['kka', 'E4', 'rowm'], [kBf])
            k.stt(Kf[c][:], kmod[:], rowm[:, c:c + 1], E4[:], ALU.mult, ALU.mult, ['kmod', 'E4', 'rowm'], [kKf])
        yield
        for g in range(2):
            HS = list(range(4 * g, 4 * g + 4))
            for h in HS:
                hl = h % 4
                cs_ = slice(h * 64, (h + 1) * 64)
                for q, (src, key) in enumerate([(At, kAt), (Bs, kBs), (Ks, kKs), (Rt, kRt)]):
                    k.tr(B[hl][0:64, q * 128:(q + 1) * 128], src[:, cs_], k.identf[:], [key], [bk(hl)])
            for h in HS:
                hl = h % 4
                k.cp('act' if h % 2 else 'dve', FT[h][:].rearrange("p a t -> p (a t)"), B[hl][0:64, :], [bk(hl)], [f'FT{h}'])
            for h in HS:
                hl = h % 4
                AtT, BsT, KsT, RtT = (FT[h][:, q, :] for q in range(4))
                k.mm(B[hl][:, 0:128], BsT, AtT, True, True, [f'FT{h}'], [bk(hl)])
                k.mm(B[hl][:, 128:256], AtT, BsT, True, True, [f'FT{h}'], [bk(hl)])
                k.mm(B[hl][:, 256:384], KsT, AtT, True, True, [f'FT{h}'], [bk(hl)])
            for h in HS:
                hl = h % 4
                k.tt('dve', A5[h][:, 0:384], B[hl][:, 0:384], mask5[:, 0:384], ALU.mult, [bk(hl), 'mask5'], [f'A5_{h}'])
            for h in HS:
                hl = h % 4
                AtT, BsT, KsT, RtT = (FT[h][:, q, :] for q in range(4))
                k.mm(B[hl][:, 0:128], BsT, RtT, True, True, [f'FT{h}'], [bk(hl)])
                k.mm(B[hl][:, 128:256], KsT, RtT, True, True, [f'FT{h}'], [bk(hl)])
            for h in HS:
                hl = h % 4
                k.tt('dve', A5[h][:, 384:640], B[hl][:, 0:256], mask5[:, 384:640], ALU.mult, [bk(hl), 'mask5'], [f'A5b_{h}'])
                k.cp('act', NL[h][:], rd(A5[h][:, 0:256]), [f'A5_{h}'], [f'NL_{h}'])
                k.tt('dve', PQ[h][:, 0:128], rd(A5[h][:, 0:128]), k.identf[:], ALU.add, [f'A5_{h}', 'ident'], [f'PQ_{h}'])
            for lev in range(nlev):
                last = (lev == nlev - 1)
                for h in HS:
                    hl = h % 4
                    N_, L_ = NL[h][:, 0:128], NL[h][:, 128:256]
                    k.mm(B[hl][:, 0:128], L_, N_, True, True, [f'NL_{h}'], [bk(hl)])
                    k.mm(B[hl][:, 128:256], N_, L_, True, True, [f'NL_{h}'], [bk(hl)])
                for h in HS:
                    hl = h % 4
                    k.cp('act', NL[h][:], B[hl][:, 0:256], [bk(hl)], [f'NL_{h}'])
                for h in HS:
                    hl = h % 4
                    k.mm(B[hl][:, 256:384], NL[h][:, 128:256], PQ[h][:, 0:128], True, True, [f'NL_{h}', f'PQ_{h}'], [bk(hl)])
                for h in HS:
                    hl = h % 4
                    k.tt('dve', PQ[h][:, 0:128], B[hl][:, 256:384], rd(PQ[h][:, 0:128]), ALU.add, [bk(hl), f'PQ_{h}'], [f'PQ_{h}'])
            yield
        for h in range(NH):
            k.mm(B[C0][:, h * 64:(h + 1) * 64], A5[h][:, 256:384], vr[:, h * 64:(h + 1) * 64], True, True, [f'A5_{h}', kvr], [bk(C0)])
        k.cp('act', W1[:], B[C0][:, 0:W], [bk(C0)], ['W1'])
        for h in range(NH):
            k.mm(B[C1][:, h * 64:(h + 1) * 64], PQ[h][:, 0:128], W1[:, h * 64:(h + 1) * 64], True, True,
                 [f'PQ_{h}', 'W1'], [bk(C1)])
        k.cp('act', U1[:], B[C1][:, 0:W], [bk(C1)], ['U1'])
        for c in range(NCK):
            cr = slice(c * CH, (c + 1) * CH)
            for h in range(NH):
                k.mm(B[C0][cr, h * 64:(h + 1) * 64], lhc(FT[h][:, 0, cr]), ST[h][:], True, True, [f'FT{h}', f'ST{h}'], [bk(C0)])
            k.cp('act', P1s[cr, :], B[C0][cr, 0:W], [bk(C0)], ['P1s'])
            for h in range(NH):
                k.mm(B[C0][cr, h * 64:(h + 1) * 64], lhc(PQ[h][:, cr]), P1s[:, h * 64:(h + 1) * 64], True, True,
                     [f'PQ_{h}', 'P1s'], [bk(C0)])
            k.tt('dve', Us[cr, :], B[C0][cr, 0:W], U1[cr, :], ALU.add, [bk(C0), 'U1'], ['Us'])
            for h in range(NH):
                hc_ = slice(h * 64, (h + 1) * 64)
                vh = vr[:, hc_] if frc else rd(vr[:, hc_])
                k.mm(B[C0][cr, hc_], lhc(FT[h][:, 3, cr]), ST[h][:], True, False, [f'FT{h}', f'ST{h}'], [bk(C0)])
                k.mm(B[C0][cr, hc_], lhc(A5[h][:, 384:512][:, cr]), Us[:, hc_], False, False, [f'A5b_{h}', 'Us'], [bk(C0)])
                k.mm(B[C0][cr, hc_], lhc(A5[h][:, 512:640][:, cr]), vh, False, True, [f'A5b_{h}', kvr], [bk(C0)])
            for h in range(NH):
                hc_ = slice(h * 64, (h + 1) * 64)
                k.mm(B[C1][0:64, hc_], Bf[c][:, hc_], rdc(Us[:, hc_]), True, False, [kBf, 'Us'], [bk(C1)])
                k.mm(B[C1][0:64, hc_], Kf[c][:, hc_], rd(vr[:, hc_]), False, True, [kKf, kvr], [bk(C1)])
            for h in range(NH):
                hc_ = slice(h * 64, (h + 1) * 64)
                k.stt(ST[h][:], rdc(ST[h][:]), E1T[:, h, (c + 1) * CH - 1:(c + 1) * CH], B[C1][0:64, hc_], ALU.mult, ALU.add,
                      [f'ST{h}', kE1T, bk(C1)], [f'ST{h}'])
        k.cp('act', ysb[:], B[C0][:, 0:W], [bk(C0)], ['ysb'])
        k.P.op('dve', lambda e: e.tensor_reduce(out=m4[:], in_=v3(ysb[:]), axis=AX.X, op=ALU.add), reads=['ysb'], writes=['m4'])
        k.ts('dve', m4[:], m4[:], -1.0 / 64.0, None, ALU.mult, None, ['m4'], ['m4'])
        k.tt('dve', v3(yc[:]), v3(ysb[:]), bc4(m4[:]), ALU.add, ['ysb', 'm4'], ['yc'])
        k.tt('pool', sqp[:], yc[:], yc[:], ALU.mult, ['yc'], ['sqp'])
        k.P.op('dve', lambda e: e.tensor_reduce(out=r4[:], in_=v3(sqp[:]), axis=AX.X, op=ALU.add), reads=['sqp'], writes=['r4'])
        k.ts('dve', r4[:], r4[:], 1.0 / 64.0, GN_EPS, ALU.mult, ALU.add, ['r4'], ['r4'])
        k.act(r4[:], r4[:], AF.Sqrt, ['r4'], ['r4'])
        k.recip(r4[:], r4[:], ['r4'], ['r4'])
        k.tt('dve', v3(yc[:]), v3(yc[:]), bc4(r4[:]), ALU.mult, ['yc', 'r4'], ['yc'])
        k.tt('pool', yc[:], yc[:], lngbc[:], ALU.mult, ['yc', VK[5]], ['yc'])
        k.tt('pool', yc[:], yc[:], lnbbc[:], ALU.add, ['yc', VK[6]], ['yc'])
        k.tt('dve', v3(tmpp[:]), v3(rd(vr[:])), bc4(bon[:]), ALU.mult, [kvr, kbon], ['tmpp'])
        k.tt('pool', yc[:], yc[:], tmpp[:], ALU.add, ['yc', 'tmpp'], ['yc'])
        k.tt('dve', ot[b][:], yc[:], gv[:], ALU.mult, ['yc', kgv], [f'ot{b}'])
        k.dma('pool', oc[rows, :], ot[b][:], r=[f'ot{b}'], final=True)

    gens = {}

    def adv(j):
        if 0 <= j < NT:
            try:
                next(gens[j])
            except StopIteration:
                pass

    for step in range(NT + 2):
        if step < NT:
            gens[step] = tile(step)
            adv(step)
        for r_i in range(3):
            adv(step - 1)
            adv(step - 2)
    return k.finish()


def rwkv_consts(CH=64):
    c = -math.exp(-0.5)
    blk = np.kron(np.eye(128 // CH), np.ones((CH, CH)))
    s_idx = np.arange(128)[:, None]
    t_idx = np.arange(128)[None, :]
    triw = np.stack([c * blk * (s_idx <= t_idx), c * blk * (s_idx < t_idx), c * blk * (s_idx > t_idx)]).astype(np.float32)
    lt_, le_, gt_ = blk * (s_idx < t_idx), blk * (s_idx <= t_idx), blk * (t_idx < s_idx)
    mask5 = np.concatenate([lt_, gt_, lt_, le_, le_], 1).astype(np.float32)
    rowm = np.stack([(np.arange(128) < 64), (np.arange(128) >= 64)], 1).astype(np.float32) if CH == 64 else np.ones((128, 2), np.float32)
    return dict(ident=np.eye(128, dtype=np.float32), triw=triw, mask5=mask5, rowm=rowm)


def rwkv_host_inputs(s, p_rwkv, prm, NH=4, CH=64):
    L = p_rwkv.shape[0]
    cs = slice(64 * NH * s, 64 * NH * (s + 1))
    r_, w1, k_, v_, a1, g1 = np.split(p_rwkv, np.cumsum([512, 64, 512, 512, 64])[:5], axis=-1)
    mu = prm['rwkv_mu']
    mur, muw1, muk, muv, mua1, mug1 = np.split(mu, np.cumsum([512, 64, 512, 512, 64])[:5])
    zm = np.zeros(64, np.float32)
    mul = np.concatenate([muw1, zm, mua1, zm, mug1]).reshape(3, 128).T
    vecs = np.stack([prm['rwkv_w0'][cs], prm['rwkv_a0'][cs], prm['rwkv_k_k'][cs], prm['rwkv_k_a'][cs],
                     prm['rwkv_r_k'].reshape(-1)[cs], prm['rwkv_ln_gain'][cs], prm['rwkv_ln_bias'][cs]])
    c_ = np.ascontiguousarray
    d = dict(pr=c_(r_[:, cs]), pk=c_(k_[:, cs]), pv=c_(v_[:, cs]),
             mu1=c_(np.concatenate([mur[cs], muk[cs], muv[cs]])),
             plw=c_(w1.T), pla=c_(a1.T), plg=c_(g1.T), mul=c_(mul),
             w2=c_(prm['rwkv_w2'][:, cs]), a2=c_(prm['rwkv_a2'][:, cs]),
             g2=c_(prm['rwkv_g2'][:, cs]), vecs=c_(vecs))
    d.update(rwkv_consts(CH))
    return d


FM0 = [(0, 128, 0), (128, 128, 128), (256, 128, 256), (384, 128, 384), (1536, 16, 512)] + \
      [(1552 + j * 128, 128, 528 + j * 128) for j in range(4)]
NF0 = 1040
FM1 = [(512, 64, 0), (1600, 64, 64), (1664, 128, 128)] + [(1792 + j * 128, 128, 256 + j * 128) for j in range(8)]
NF1 = 1280


def host_params(inp):
    c_ = lambda a: np.ascontiguousarray(np.asarray(a), dtype=np.float32)
    P = {}
    P['ident'] = np.eye(128, dtype=np.float32)
    P['triu'] = np.triu(np.ones((128, 128), np.float32))
    P['trigt'] = np.tril(np.ones((128, 128), np.float32), -1)
    for l in range(2):
        for j in range(7):
            P[f'g{l}_{j}'] = c_(inp['norm_gain'][l][j])
        for nm in ('xa_wq', 'xa_wk', 'xa_wv', 'xa_wo', 'mlp_w1', 'mlp_w2'):
            P[f'{nm}{l}'] = c_(inp[nm][l])
    P['w_in0'] = c_(inp['ab_w_in'][0])
    P['w_in1'] = c_(inp['cd_w_in'][0])
    P['w_out0'] = c_(inp['ab_w_out'][0])
    P['w_out1'] = c_(inp['cd_w_out'][0])
    P['wglu'] = c_(inp['s5_w_glu'][0])
    P['bglu'] = c_(inp['s5_b_glu'][0])
    prm0 = {k_: np.asarray(inp[k_][0]) for k_ in inp if k_.startswith('s5_') or k_.startswith('gla_')}
    prm1 = {k_: np.asarray(inp[k_][0]) for k_ in inp if k_.startswith('rwkv_') or k_.startswith('lru_')}
    for s in range(2):
        cs = slice(s * 128, (s + 1) * 128)
        P[f'gla_w2_{s}'] = c_(prm0['gla_w_decay2'][:, cs])
        P[f'gla_bd_{s}'] = c_(prm0['gla_b_decay'][None, cs])
        P[f'gla_gn_{s}'] = c_(prm0['gla_norm_gain'][2 * s:2 * s + 2].reshape(256))
        d = s5_host_inputs(s, np.zeros((2, 512), np.float32), prm0)
        for nm in ('lam_re', 'lam_im', 'lstep', 'Bre', 'Bim', 'Cre', 'Cim', 'dsk'):
            P[f's5_{nm}_{s}'] = c_(d[nm])
        P['iota_p'] = c_(d['iota_p'])
        P['iota_f'] = c_(d['iota_f'])
        if s == 0:
            d = rwkv_host_inputs(0, np.zeros((2, 1792), np.float32), prm1, 8, 64)
            for nm in ('mu1', 'mul', 'w2', 'a2', 'g2', 'vecs'):
                P[f'rw_{nm}'] = c_(d[nm])
            for nm in ('triw', 'mask5', 'rowm'):
                P[f'rw_{nm}'] = c_(d[nm])
        d = lru_host_inputs(s, np.zeros((2, 512), np.float32), np.zeros((2, 512), np.float32), prm1)
        for nm in ('cw', 'cb', 'Wa', 'Wx', 'ba', 'bx', 'lam'):
            P[f'lru_{nm}_{s}'] = c_(d[nm])
    return P


def build_fused(P, L):
    k = K(fused=True)
    X = {nm: k.xin(nm, a.shape) for nm, a in P.items()}
    x = k.xin('x', [L, D])
    mem = k.xin('mem', [256, D])
    out = k.xout('out', [L, D])
    proj0 = k.scratch('proj0', [L, 2064])
    PT0 = k.scratch('PT0', [NF0, L])
    proj1 = k.scratch('proj1', [L, 2816])
    PT1 = k.scratch('PT1', [NF1, L])
    o = k.scratch('o', [L, D])
    odT = k.scratch('odT', [512, L])
    h1 = k.scratch('h1', [L, D])
    h2 = k.scratch('h2', [L, D])
    h3 = k.scratch('h3', [L, D])

    def cblock(l, hin, hout, glu, ob_fm):
        io = dict(oa=o[:, 0:512], hin=hin, wout=X[f'w_out{l}'], g1=X[f'g{l}_1'], ident=X['ident'], hout=h1)
        if ob_fm:
            io['obT'] = odT
        else:
            io['ob'] = o[:, 512:1024]
        if glu:
            io.update(wglu=X['wglu'], bglu=X['bglu'])
        k.begin_phase(f'C1_{l}', io)
        build_C1(L, glu, k=k, ob_fm=ob_fm)
        k.begin_phase(f'C2_{l}', dict(hin=h1, mem=mem, wq=X[f'xa_wq{l}'], wk=X[f'xa_wk{l}'], wv=X[f'xa_wv{l}'], wo=X[f'xa_wo{l}'],
                                      g2=X[f'g{l}_2'], g3=X[f'g{l}_3'], g6=X[f'g{l}_6'], ident=X['ident'], hout=h2))
        build_C2(L, k=k)
        k.begin_phase(f'C3_{l}', dict(hin=h2, w1=X[f'mlp_w1{l}'], w2=X[f'mlp_w2{l}'], g4=X[f'g{l}_4'], g5=X[f'g{l}_5'],
                                      ident=X['ident'], hout=hout))
        build_C3(L, k=k)

    k.begin_phase('A0', dict(x=x, gain=X['g0_0'], W=X['w_in0'], ident=X['ident'], out=proj0, outT=PT0))
    build_A2(L, 2064, FM0, NF0, k=k)
    for s in range(2):
        io_g = dict(qT=PT0[s * 128:(s + 1) * 128, :], kT=PT0[256 + s * 128:256 + (s + 1) * 128, :],
                    ktok=proj0[:, 256 + s * 128:256 + (s + 1) * 128], v=proj0[:, 512 + s * 256:512 + (s + 1) * 256],
                    gate=proj0[:, 1024 + s * 256:1024 + (s + 1) * 256], dlrT=PT0[512:528, :],
                    w2=X[f'gla_w2_{s}'], bdec=X[f'gla_bd_{s}'], gn=X[f'gla_gn_{s}'], triu=X['triu'],
                    trigt=X['trigt'], oa=o[:, s * 256:(s + 1) * 256])
        io_s = dict(uT=PT0[528 + s * 256:528 + (s + 1) * 256, :], u=proj0[:, 1552 + s * 256:1552 + (s + 1) * 256],
                    triu=X['triu'], iota_p=X['iota_p'], iota_f=X['iota_f'], y=o[:, 512 + s * 256:512 + (s + 1) * 256])
        for nm in ('lam_re', 'lam_im', 'lstep', 'Bre', 'Bim', 'Cre', 'Cim', 'dsk'):
            io_s[nm] = X[f's5_{nm}_{s}']
        k.begin_phase(f'B0{s}', {})
        run_streams(k, [('s_', io_s, lambda kk: gen_S5(L, kk)), ('g_', io_g, lambda kk: gen_GLA(L, kk))])
        k.finish()
    cblock(0, x, h3, True, False)
    k.begin_phase('A1', dict(x=h3, gain=X['g1_0'], W=X['w_in1'], ident=X['ident'], out=proj1, outT=PT1))
    build_A2(L, 2816, FM1, NF1, k=k)
    io = dict(pr=proj1[:, 0:512], pk=proj1[:, 576:1088], pv=proj1[:, 1088:1600], plw=PT1[0:64, :], pla=PT1[64:128, :],
              plg=PT1[128:256, :], ident=X['ident'], triw=X['rw_triw'], mask5=X['rw_mask5'], rowm=X['rw_rowm'], oc=o[:, 0:512])
    for nm in ('mu1', 'mul', 'w2', 'a2', 'g2', 'vecs'):
        io[nm] = X[f'rw_{nm}']
    k.begin_phase('RW', io)
    build_RWKVP(L, k=k, CH=64)
    streams = []
    for s in range(2):
        io = dict(xbT=PT1[256 + s * 256:256 + (s + 1) * 256, :], gateT=PT1[768 + s * 256:768 + (s + 1) * 256, :],
                  odT=odT[s * 256:(s + 1) * 256, :])
        for nm in ('cw', 'cb', 'Wa', 'Wx', 'ba', 'bx', 'lam'):
            io[nm] = X[f'lru_{nm}_{s}']
        streams.append((f'l{s}_', io, lambda kk: gen_LRU(L, kk)))
    k.begin_phase('LRU', {})
    run_streams(k, streams)
    k.finish()
    cblock(1, h3, out, False, True)
    return k.finish_program()


BATCH, SEQ = 4, 4096
_CACHE = {}


def kernel(**inp):
    inp = {k_: np.asarray(v_) for k_, v_ in inp.items()}
    P = host_params(inp)
    if 'nc' not in _CACHE:
        _CACHE['nc'] = build_fused(P, SEQ)
    nc = _CACHE['nc']
    maps = []
    for b in range(BATCH):
        m = dict(P)
        m['x'] = np.ascontiguousarray(inp['x'][b], dtype=np.float32)
        m['mem'] = np.ascontiguousarray(inp['mem'][b], dtype=np.float32)
        maps.append(m)
    res = run_bass_kernel_spmd(nc, maps, core_ids=list(range(BATCH))).results
    return np.ascontiguousarray(np.stack([res[b]['out'] for b in range(BATCH)]).astype(np.float32))
```

```python
import os
import math
from contextlib import ExitStack


import numpy as np
import concourse.bass as bass
import concourse.mybir as mybir
from concourse.bass_utils import run_bass_kernel_spmd

F32 = mybir.dt.float32
BF16 = mybir.dt.bfloat16
I32 = mybir.dt.int32
AF = mybir.ActivationFunctionType
ALU = mybir.AluOpType
AX = mybir.AxisListType

ENGS = ['pe', 'act', 'dve', 'pool', 'sp']
NDMA_SLOTS = 8
SAME_ENGINE_SYNC = os.environ.get("NOSELF", "0") != "1"


class Prog:
    def __init__(self, nc):
        self.nc = nc
        self.ops = {e: [] for e in ENGS}
        self.cnt = {e: 0 for e in ENGS}
        self.last_w = {}
        self.readers = {}
        self.seen = {e: {} for e in ENGS}
        self.dma_n = {e: 0 for e in ENGS}
        self.dma_tok = {e: [None] * NDMA_SLOTS for e in ENGS}
        self.final_tokens = []
        from contextlib import ExitStack
        self.sem_stack = ExitStack()
        self.sems = {}
        for e in ['pe', 'act', 'dve', 'pool']:
            self.sems[('c', e)] = self.sem_stack.enter_context(nc.semaphore("s_c_" + e))
        for q in ['sp', 'pool']:
            for sl in range(NDMA_SLOTS):
                self.sems[('d', q, sl)] = self.sem_stack.enter_context(nc.semaphore(f"s_d_{q}_{sl}"))

    def barrier(self):
        toks = []
        for e in ['pe', 'act', 'dve', 'pool']:
            if self.cnt[e] > 0:
                toks.append((('c', e), self.cnt[e]))
        for q in ENGS:
            for t in self.dma_tok[q]:
                if t is not None:
                    toks.append(t)
        for e in ENGS:
            waits = []
            for (sem, val) in toks:
                if sem == ('c', e):
                    continue
                if self.seen[e].get(sem, 0) >= val:
                    continue
                waits.append((sem, val))
                self.seen[e][sem] = val
            if waits:
                self.ops[e].append((waits, None, None))
        self.last_w = {}
        self.readers = {}

    def _deps(self, eng, reads, writes):
        toks = []
        for r in reads:
            t = self.last_w.get(r)
            if t is not None:
                toks.append(t)
        for w in writes:
            t = self.last_w.get(w)
            if t is not None:
                toks.append(t)
            toks.extend(self.readers.get(w, []))
        need = {}
        for (sem, val) in toks:
            if not SAME_ENGINE_SYNC and sem == ('c', eng):
                continue
            if sem == ('c', 'pe') and eng == 'pe':
                continue
            if self.seen[eng].get(sem, 0) >= val:
                continue
            if need.get(sem, 0) < val:
                need[sem] = val
        for sem, val in need.items():
            self.seen[eng][sem] = val
        return list(need.items())

    def _commit(self, tok, reads, writes):
        for w in writes:
            self.last_w[w] = tok
            self.readers[w] = []
        for r in reads:
            if r in writes:
                continue
            self.readers.setdefault(r, []).append(tok)

    def op(self, eng, fn, reads=(), writes=()):
        self.nrec = getattr(self, 'nrec', 0) + 1
        if self.nrec > int(os.environ.get("MAXOPS", "100000000")):
            return None
        kp = getattr(self, 'key_prefix', '')
        reads = [r if r.startswith('ps') else kp + r for r in reads]
        writes = [w if w.startswith('ps') else kp + w for w in writes]
        pk = getattr(self, 'ps_prefix', '')
        reads = [('ps' + pk + r[2:]) if r.startswith('ps') else r for r in reads]
        writes = [('ps' + pk + w[2:]) if w.startswith('ps') else w for w in writes]
        writes = list(writes) + [r for r in reads if r.startswith('ps') and r not in writes]
        waits = self._deps(eng, reads, writes)
        self.cnt[eng] += 1
        tok = (('c', eng), self.cnt[eng])
        self.ops[eng].append((waits, fn, tok))
        self._commit(tok, reads, writes)
        return tok

    def dma(self, q, out, in_, reads=(), writes=(), final=False, **kw):
        self.nrec = getattr(self, 'nrec', 0) + 1
        if self.nrec > int(os.environ.get("MAXOPS", "100000000")):
            return None
        kp = getattr(self, 'key_prefix', '')
        reads = [kp + r for r in reads]
        writes = [kp + w for w in writes]
        waits = self._deps(q, reads, writes)
        n = self.dma_n[q]
        slot = n % NDMA_SLOTS
        prev = self.dma_tok[q][slot]
        if prev is not None and self.seen[q].get(prev[0], 0) < prev[1]:
            waits.append(prev)
            self.seen[q][prev[0]] = prev[1]
        tok = (('d', q, slot), 16 * (n // NDMA_SLOTS + 1))
        self.dma_n[q] += 1
        self.dma_tok[q][slot] = tok

        def fn(e, out=out, in_=in_, kw=kw):
            return e.dma_start(out=out, in_=in_, **kw)
        self.ops[q].append((waits, fn, tok))
        self._commit(tok, reads, writes)
        if final:
            self.final_tokens.append(tok)
        return tok

    def emit(self, last=True):
        nc = self.nc
        sems = self.sems
        with nc.Block() as block:
            final = list(self.final_tokens) if last else []

            def run(e, name):
                for waits, fn, tok in self.ops[name]:
                    for (s, v) in waits:
                        e.wait_ge(sems[s], v)
                    if fn is None:
                        continue
                    inst = fn(e)
                    inc = 16 if tok[0][0] == 'd' else 1
                    inst.then_inc(sems[tok[0]], inc)
                if name == 'sp':
                    for (s, v) in final:
                        e.wait_ge(sems[s], v)
                self.ops[name] = []

            @block.tensor
            def _(e):
                run(e, 'pe')

            @block.scalar
            def _(e):
                run(e, 'act')

            @block.vector
            def _(e):
                run(e, 'dve')

            @block.gpsimd
            def _(e):
                run(e, 'pool')

            @block.sync
            def _(e):
                run(e, 'sp')
        if last:
            self.sem_stack.close()


D = 1024
KC = 8
EPS = 1e-6


class K:
    def __init__(self, fused=False):
        self.nc = bass.Bass("TRN2", target_bir_lowering=False)
        self.st = ExitStack()
        self.P = Prog(self.nc)
        self.n = 0
        self.fused = fused
        self.io = {}
        self.pfx = ""

    def begin_phase(self, name, io):
        self.pfx = name + "_"
        self.io = io
        self.st = ExitStack()
        for a in ('wstage', 'rr_cache', 'identf', 'identb'):
            if hasattr(self, a):
                delattr(self, a)

    def scratch(self, name, shape, dt=F32):
        return self.nc.dram_tensor(name, list(shape), dt, kind="Internal").ap()

    def xin(self, name, arr_shape, dt=F32):
        return self.nc.dram_tensor(name, list(arr_shape), dt, kind="ExternalInput").ap()

    def xout(self, name, arr_shape, dt=F32):
        return self.nc.dram_tensor(name, list(arr_shape), dt, kind="ExternalOutput").ap()

    def din(self, name, shape, dt=F32):
        if self.fused:
            ap = self.io[name]
            assert list(ap.shape) == list(shape), (name, ap.shape, shape)
            return ap
        return self.nc.dram_tensor(name, list(shape), dt, kind="ExternalInput").ap()

    def dout(self, name, shape, dt=F32):
        if self.fused:
            ap = self.io[name]
            assert list(ap.shape) == list(shape), (name, ap.shape, shape)
            return ap
        return self.nc.dram_tensor(name, list(shape), dt, kind="ExternalOutput").ap()

    def sb(self, name, shape, dt=F32):
        pers = getattr(self, 'persist', None)
        if pers is not None and (self.pfx + name) in pers:
            return pers[self.pfx + name]
        return self.st.enter_context(self.nc.sbuf_tensor(self.pfx + name, list(shape), dt))

    def push_scope(self, persistent):
        self.persist = getattr(self, 'persist', None) or {}
        for (name, shape, dt) in persistent:
            self.persist[self.pfx + name] = self.st.enter_context(self.nc.sbuf_tensor(self.pfx + name, list(shape), dt))
        self._st_saved = self.st
        self.st = ExitStack()

    def pop_scope(self):
        self.P.barrier()
        self.P.emit(last=False)
        self.st.close()
        self.st = self._st_saved

    def ps(self, name, shape, dt=F32):
        return self.st.enter_context(self.nc.psum_tensor(self.pfx + name, list(shape), dt))

    def finish(self, last=True):
        if self.fused:
            self.P.barrier()
            self.P.emit(last=False)
            self.st.close()
            return None
        self.P.emit()
        self.st.close()
        return self.nc

    def finish_program(self):
        self.P.emit(last=True)
        return self.nc

    def mm(self, out, lhsT, rhs, start, stop, r, w):
        self.P.op('pe', lambda e: e.matmul(out, lhsT=lhsT, rhs=rhs, start=start, stop=stop), reads=r, writes=w)

    def tr(self, out, in_, ident, r, w):
        self.P.op('pe', lambda e: e.transpose(out=out, in_=in_, identity=ident), reads=list(r) + ['ident'], writes=w)

    def act(self, out, in_, func, r, w, **kw):
        self.P.op('act', lambda e: e.activation(out=out, in_=in_, func=func, **kw), reads=r, writes=w)

    def tt(self, eng, out, in0, in1, op, r, w):
        self.P.op(eng, lambda e: e.tensor_tensor(out=out, in0=in0, in1=in1, op=op), reads=r, writes=w)

    def ts(self, eng, out, in0, s1, s2, op0, op1, r, w):
        if op1 is None:
            self.P.op(eng, lambda e: e.tensor_scalar(out=out, in0=in0, scalar1=s1, scalar2=None, op0=op0), reads=r, writes=w)
        else:
            self.P.op(eng, lambda e: e.tensor_scalar(out=out, in0=in0, scalar1=s1, scalar2=s2, op0=op0, op1=op1), reads=r, writes=w)

    def stt(self, out, in0, scalar, in1, op0, op1, r, w):
        self.P.op('dve', lambda e: e.scalar_tensor_tensor(out=out, in0=in0, scalar=scalar, in1=in1, op0=op0, op1=op1),
                  reads=r, writes=w)

    def cp(self, eng, out, in_, r, w):
        if eng == 'act':
            self.P.op('act', lambda e: e.copy(out=out, in_=in_), reads=r, writes=w)
        else:
            self.P.op(eng, lambda e: e.tensor_copy(out=out, in_=in_), reads=r, writes=w)

    def recip(self, out, in_, r, w):
        self.P.op('dve', lambda e: e.reciprocal(out=out, in_=in_), reads=r, writes=w)

    def memset(self, eng, ap, val, w):
        self.P.op(eng, lambda e: e.memset(ap, val), reads=[], writes=w)

    def dma(self, q, out, in_, r=(), w=(), final=False, **kw):
        self.P.dma(q, out, in_, reads=r, writes=w, final=final, **kw)

    def consts(self, ident_d):
        self.identf = self.sb("identf", [128, 128], F32)
        self.identb = self.sb("identb", [128, 128], BF16)
        self.dma('sp', self.identf[:], ident_d, w=['ident'])
        self.cp('dve', self.identb[:], self.identf[:], ['ident'], ['ident'])

    def gain_cols(self, name, g_d):
        t = self.sb(name, [128, KC], F32)
        self.dma('sp', t[:], g_d.rearrange("(kc p) -> p kc", p=128), w=[name], allow_slow_non_contiguous=True)
        return t

    def bcast_row(self, name, vec_d, n):
        t = self.sb(name, [128, n], F32)
        self.dma('sp', t[:], vec_d.partition_broadcast(128), w=[name])
        return t

    def load_weight(self, name, w_d, kchunks, ncols, gcol=None, gkey=None, stage_cols=2048, q='sp'):
        wb = self.sb(name, [128, kchunks, ncols], BF16)
        if not hasattr(self, 'wstage'):
            self.wstage = [self.sb(f"wstage{i}", [128, stage_cols], F32) for i in range(2)]
            self.wstage_n = 0
            self.wstage_cols = stage_cols
        sc = self.wstage_cols
        wv = w_d.rearrange("(kc p) n -> p kc n", p=128)
        for kc in range(kchunks):
            for c0 in range(0, ncols, sc):
                cw = min(sc, ncols - c0)
                b = self.wstage_n % 2
                self.wstage_n += 1
                stg = self.wstage[b]
                self.dma(q, stg[:, 0:cw], wv[:, kc, c0:c0 + cw], w=[f'wstage{b}'])
                eng = 'act' if (kc % 2 == 0) else 'dve'
                if gcol is not None:
                    if eng == 'act':
                        self.act(wb[:, kc, c0:c0 + cw], stg[:, 0:cw], AF.Copy, [f'wstage{b}', gkey], [f'{name}{kc}'],
                                 scale=gcol[:, kc:kc + 1])
                    else:
                        self.ts('dve', wb[:, kc, c0:c0 + cw], stg[:, 0:cw], gcol[:, kc:kc + 1], None, ALU.mult, None,
                                [f'wstage{b}', gkey], [f'{name}{kc}'])
                else:
                    self.cp(eng, wb[:, kc, c0:c0 + cw], stg[:, 0:cw], [f'wstage{b}'], [f'{name}{kc}'])
        return wb

    def rstd_of(self, x_ap, xkey, ss, rstd, junk, key, ncols=D):
        self.act(junk, x_ap, AF.Square, [xkey], ['junk', key + 'ss'], accum_out=ss)
        self.ts('dve', rstd, ss, 1.0 / ncols, EPS, ALU.mult, ALU.add, [key + 'ss'], [key])
        self.act(rstd, rstd, AF.Sqrt, [key], [key])
        self.recip(rstd, rstd, [key], [key])


def pipeline(make_gen, n):
    active = []
    for i in range(n):
        for g in list(active):
            try:
                next(g)
            except StopIteration:
                active.remove(g)
        g = make_gen(i)
        active.append(g)
        try:
            next(g)
        except StopIteration:
            active.remove(g)
    while active:
        for g in list(active):
            try:
                next(g)
            except StopIteration:
                active.remove(g)


def pipeline_gen(make_gen, n):
    active = []
    for i in range(n):
        for g in list(active):
            try:
                next(g)
            except StopIteration:
                active.remove(g)
        g = make_gen(i)
        active.append(g)
        try:
            next(g)
        except StopIteration:
            active.remove(g)
        yield
    while active:
        for g in list(active):
            try:
                next(g)
            except StopIteration:
                active.remove(g)
        yield


def run_streams(k, streams):
    base_pfx = k.pfx
    gens = []
    for (pf, io, gf) in streams:
        gens.append([pf, io, None, gf])
    active = list(gens)
    while active:
        for st in list(active):
            pf, io, g, gf = st
            k.pfx = base_pfx + pf
            k.P.key_prefix = pf
            k.P.ps_prefix = pf
            k.io = io
            try:
                if g is None:
                    st[2] = gf(k)
                    g = st[2]
                next(g)
            except StopIteration:
                active.remove(st)
    k.pfx = base_pfx
    k.P.key_prefix = ''
    k.P.ps_prefix = ''


GELU_C = 1.5957691216057308


def norm_T(k, xt, xkey, xn, xnkey, xT_dst, xTkey, psT, psTkey, ss, rstd, junk, key, evac_eng='act'):
    k.rstd_of(xt, xkey, ss, rstd, junk, key)
    k.ts('dve', xn, xt, rstd, None, ALU.mult, None, [xkey, key], [xnkey])
    for kc in range(KC):
        k.tr(psT[:, kc * 128:(kc + 1) * 128], xn[:, kc * 128:(kc + 1) * 128], k.identb[:], [xnkey], [psTkey])
    k.cp(evac_eng, xT_dst, psT[:].rearrange("p (k t) -> p k t", k=KC), [psTkey], [xTkey])


def post_norm_res(k, ps2, pskeys, ht, hkey, gbc, gkey, tmp2, tmpkeys, ss2, rstd, junk, key):
    for j in range(2):
        k.act(junk[:, 0:512], ps2[j], AF.Square, [pskeys[j]], ['junk', key + f'ss{j}'], accum_out=ss2[:, j:j + 1])
    k.tt('dve', ss2[:, 0:1], ss2[:, 0:1], ss2[:, 1:2], ALU.add, [key + 'ss0', key + 'ss1'], [key + 'ss0'])
    k.ts('dve', rstd, ss2[:, 0:1], 1.0 / D, EPS, ALU.mult, ALU.add, [key + 'ss0'], [key])
    k.act(rstd, rstd, AF.Sqrt, [key], [key])
    k.recip(rstd, rstd, [key], [key])
    for j in range(2):
        sl = slice(j * 512, (j + 1) * 512)
        k.stt(tmp2[j], ps2[j], rstd, gbc[:, sl], ALU.mult, ALU.mult, [pskeys[j], key, gkey], [tmpkeys[j]])
        k.tt('pool', ht[:, sl], ht[:, sl], tmp2[j], ALU.add, [tmpkeys[j], hkey], [hkey])


def build_C1(NTOK, glu, k=None, ob_fm=False):
    k = k or K()
    NT = NTOK // 128
    NB = 3
    oa = k.din("oa", [NTOK, 512])
    if ob_fm:
        obT = k.din("obT", [512, NTOK])
    else:
        ob = k.din("ob", [NTOK, 512])
    hin = k.din("hin", [NTOK, D])
    wout = k.din("wout", [D, D])
    g1 = k.din("g1", [D])
    ident_d = k.din("ident", [128, 128])
    if glu:
        wglu = k.din("wglu", [512, 512])
        bglu = k.din("bglu", [512])
    hout = k.dout("hout", [NTOK, D])
    k.consts(ident_d)
    g1bc = k.bcast_row("g1bc", g1, D)
    Wout = k.load_weight("Wout", wout, KC, D, stage_cols=1024)
    if glu:
        Wglu = k.load_weight("Wglu", wglu, 4, 512)
        bgbc = k.bcast_row("bgbc", bglu, 512)
    R = range(NB)
    oc = [k.sb(f"oc{i}", [128, D]) for i in R]
    ocb = [k.sb(f"ocb{i}", [128, D], BF16) for i in R]
    oT = [k.sb(f"oT{i}", [128, KC, 128], BF16) for i in R]
    ht = [k.sb(f"ht{i}", [128, D]) for i in R]
    if ob_fm:
        obt = [k.sb(f"obt{i}", [128, 4, 128]) for i in R]
    tmp = [[k.sb(f"tmp{i}_{j}", [128, 512]) for j in range(2)] for i in range(2)]
    junk = k.sb("junk", [128, D], BF16)
    ss2 = [k.sb(f"ss2{i}", [128, 2]) for i in R]
    rstd = [k.sb(f"rstd{i}", [128, 1]) for i in R]
    if glu:
        yb = [k.sb(f"yb{i}", [128, 512], BF16) for i in R]
        yT = [k.sb(f"yT{i}", [128, 4, 128], BF16) for i in R]
        zs = [k.sb(f"zs{i}", [128, 512]) for i in R]
        t1 = [k.sb(f"t1{i}", [128, 512]) for i in R]
        t2 = [k.sb(f"t2{i}", [128, 512]) for i in R]
    psT = [k.ps(f"psT{i}", [128, D], BF16) for i in range(2)]
    psM = [k.ps(f"psM{i}", [128, 512]) for i in range(4)]
    if glu:
        psG = [k.ps(f"psG{i}", [128, 512]) for i in range(2)]

    def tile(i):
        b = i % NB
        b2 = i % 2
        rows = slice(i * 128, (i + 1) * 128)
        k.dma('sp', oc[b][:, 0:512], oa[rows, :], w=[f'oA{b}'])
        if ob_fm:
            k.dma('sp', obt[b][:], obT[:, rows].rearrange("(a p) t -> p a t", p=128), w=[f'obt{b}'])
        else:
            k.dma('sp', oc[b][:, 512:1024], ob[rows, :], w=[f'oB{b}'])
        k.dma('sp', ht[b][:], hin[rows, :], w=[f'ht{b}'])
        if glu:
            y = oc[b][:, 512:1024]
            k.cp('dve', yb[b][:], y, [f'oB{b}'], [f'yb{b}'])
            for kc in range(4):
                k.tr(psT[b2][:, kc * 128:(kc + 1) * 128], yb[b][:, kc * 128:(kc + 1) * 128], k.identb[:], [f'yb{b}'], [f'psT{b2}'])
            k.cp('act', yT[b][:], psT[b2][:, 0:512].rearrange("p (k t) -> p k t", k=4), [f'psT{b2}'], [f'yT{b}'])
            for kc in range(4):
                k.mm(psG[b2][:], yT[b][:, kc, :], Wglu[:, kc, :], kc == 0, kc == 3, [f'yT{b}', f'Wglu{kc}'], [f'psG{b2}'])
            k.act(t1[b][:], y, AF.Square, [f'oB{b}'], [f't1{b}'])
            k.act(t1[b][:], t1[b][:], AF.Copy, [f't1{b}'], [f't1{b}'], scale=0.044715, bias=1.0)
            k.tt('pool', t1[b][:], t1[b][:], y, ALU.mult, [f't1{b}', f'oB{b}'], [f't1{b}'])
            k.act(t1[b][:], t1[b][:], AF.Sigmoid, [f't1{b}'], [f't1{b}'], scale=GELU_C)
            yield
            k.tt('dve', zs[b][:], psG[b2][:], bgbc[:], ALU.add, [f'psG{b2}', 'bgbc'], [f'zs{b}'])
            k.act(zs[b][:], zs[b][:], AF.Sigmoid, [f'zs{b}'], [f'zs{b}'])
            k.tt('dve', t2[b][:], t1[b][:], zs[b][:], ALU.mult, [f't1{b}', f'zs{b}'], [f't2{b}'])
            k.tt('dve', y, y, t2[b][:], ALU.mult, [f'oB{b}', f't2{b}'], [f'oB{b}'])
        if ob_fm:
            k.cp('dve', ocb[b][:, 0:512], oc[b][:, 0:512], [f'oA{b}'], [f'ocb{b}'])
            for kc in range(4):
                k.tr(psT[b2][:, kc * 128:(kc + 1) * 128], ocb[b][:, kc * 128:(kc + 1) * 128], k.identb[:], [f'ocb{b}'], [f'psT{b2}'])
            k.cp('act', oT[b][:, 0:4, :], psT[b2][:, 0:512].rearrange("p (k t) -> p k t", k=4), [f'psT{b2}'], [f'oT{b}'])
            k.cp('pool', oT[b][:, 4:8, :], obt[b][:], [f'obt{b}'], [f'oTb{b}'])
        else:
            k.cp('dve', ocb[b][:], oc[b][:], [f'oA{b}', f'oB{b}'], [f'ocb{b}'])
            for kc in range(KC):
                k.tr(psT[b2][:, kc * 128:(kc + 1) * 128], ocb[b][:, kc * 128:(kc + 1) * 128], k.identb[:], [f'ocb{b}'], [f'psT{b2}'])
            k.cp('act', oT[b][:], psT[b2][:].rearrange("p (k t) -> p k t", k=KC), [f'psT{b2}'], [f'oT{b}'])
        yield
        for cg in range(2):
            pm = 2 * b2 + cg
            for kc in range(KC):
                ok_ = f'oTb{b}' if (ob_fm and kc >= 4) else f'oT{b}'
                k.mm(psM[pm][:], oT[b][:, kc, :], Wout[:, kc, cg * 512:(cg + 1) * 512], kc == 0, kc == KC - 1,
                     [ok_, f'Wout{kc}'], [f'psM{pm}'])
        post_norm_res(k, [psM[2 * b2][:], psM[2 * b2 + 1][:]], [f'psM{2 * b2}', f'psM{2 * b2 + 1}'], ht[b], f'ht{b}',
                      g1bc, 'g1bc', [tmp[b2][0][:], tmp[b2][1][:]], [f'tmp{b2}0', f'tmp{b2}1'], ss2[b], rstd[b][:], junk, f'pn{b}')
        k.dma('pool', hout[rows, :], ht[b][:], r=[f'ht{b}'], final=True)

    pipeline(tile, NT)
    return k.finish()


def build_C3(NTOK, k=None):
    k = k or K()
    NB = NTOK // 512
    DFF = 4096
    FC = DFF // 128
    hin = k.din("hin", [NTOK, D])
    w1 = k.din("w1", [D, DFF])
    w2 = k.din("w2", [DFF, D])
    g4 = k.din("g4", [D])
    g5 = k.din("g5", [D])
    ident_d = k.din("ident", [128, 128])
    hout = k.dout("hout", [NTOK, D])
    k.consts(ident_d)
    g4c = k.gain_cols("g4c", g4)
    g5bc = k.bcast_row("g5bc", g5, D)
    W1 = k.load_weight("W1", w1, KC, DFF, gcol=g4c, gkey='g4c', stage_cols=512)
    W2 = k.load_weight("W2", w2, FC, D, stage_cols=512)
    ht = [k.sb(f"ht{i}", [128, D]) for i in range(4)]
    xn = [k.sb(f"xn{i}", [128, D], BF16) for i in range(2)]
    xT = k.sb("xT", [128, KC, 512], BF16)
    AT = k.sb("AT", [128, FC, 512], BF16)
    sq = [k.sb(f"sq{i}", [128, 512]) for i in range(2)]
    junk = k.sb("junk", [128, D], BF16)
    ss = [k.sb(f"ss{i}", [128, 1]) for i in range(2)]
    ss2 = [k.sb(f"ss2{i}", [128, 2]) for i in range(2)]
    rstd = [k.sb(f"rstd{i}", [128, 1]) for i in range(2)]
    rstd2 = [k.sb(f"rstdb{i}", [128, 1]) for i in range(2)]
    psT = k.ps("psT", [128, D], BF16)
    psU = [k.ps(f"psU{i}", [128, 512]) for i in range(3)]
    psD = [k.ps(f"psD{i}", [128, 512]) for i in range(4)]
    nu = 0
    for blk in range(NB):
        for tt in range(4):
            i = blk * 4 + tt
            b = i % 2
            rows = slice(i * 128, (i + 1) * 128)
            k.dma('sp', ht[tt][:], hin[rows, :], w=[f'ht{tt}'])
            norm_T(k, ht[tt][:], f'ht{tt}', xn[b][:], f'xn{b}', xT[:, :, tt * 128:(tt + 1) * 128], 'xT', psT[:], 'psT',
                   ss[b][:], rstd[b][:], junk[:], f'n{b}')
        for fc in range(FC):
            pu = nu % 3
            nu += 1
            for kc in range(KC):
                k.mm(psU[pu][:], W1[:, kc, fc * 128:(fc + 1) * 128], xT[:, kc, :], kc == 0, kc == KC - 1,
                     [f'W1{kc}', 'xT'], [f'psU{pu}'])
            sb_ = fc % 2
            k.act(sq[sb_][:], psU[pu][:], AF.Square, [f'psU{pu}'], [f'sq{sb_}'])
            k.stt(AT[:, fc, :], psU[pu][:], 0.0, sq[sb_][:], ALU.is_gt, ALU.mult, [f'psU{pu}', f'sq{sb_}'], ['AT'])
        for tt in range(4):
            i = blk * 4 + tt
            b = i % 2
            rows = slice(i * 128, (i + 1) * 128)
            for cg in range(2):
                pd = 2 * b + cg
                for fc in range(FC):
                    k.mm(psD[pd][:], AT[:, fc, tt * 128:(tt + 1) * 128], W2[:, fc, cg * 512:(cg + 1) * 512],
                         fc == 0, fc == FC - 1, ['AT', f'W2{fc}'], [f'psD{pd}'])
            post_norm_res(k, [psD[2 * b][:], psD[2 * b + 1][:]], [f'psD{2 * b}', f'psD{2 * b + 1}'], ht[tt], f'ht{tt}',
                          g5bc, 'g5bc', [sq[0][:], sq[1][:]], ['sq0', 'sq1'], ss2[b], rstd2[b][:], junk, f'pn{b}')
            k.dma('pool', hout[rows, :], ht[tt][:], r=[f'ht{tt}'], final=True)
    return k.finish()


def build_C2(NTOK, k=None):
    k = k or K()
    NB = NTOK // 512
    MEM = 256
    hin = k.din("hin", [NTOK, D])
    mem = k.din("mem", [MEM, D])
    wq = k.din("wq", [D, D])
    wk = k.din("wk", [D, D])
    wv = k.din("wv", [D, D])
    wo = k.din("wo", [D, D])
    g2 = k.din("g2", [D])
    g3 = k.din("g3", [D])
    g6 = k.din("g6", [D])
    ident_d = k.din("ident", [128, 128])
    hout = k.dout("hout", [NTOK, D])
    k.consts(ident_d)
    g2c = k.gain_cols("g2c", g2)
    g6c = k.gain_cols("g6c", g6)
    g3bc = k.bcast_row("g3bc", g3, D)
    Wk = k.load_weight("Wk", wk, KC, D, gcol=g6c, gkey='g6c', stage_cols=1024)
    Wv = k.load_weight("Wv", wv, KC, D, gcol=g6c, gkey='g6c', stage_cols=1024)
    Wq = k.load_weight("Wq", wq, KC, D, gcol=g2c, gkey='g2c', stage_cols=1024)
    Wo = k.load_weight("Wo", wo, KC, D, stage_cols=1024)
    ht = [k.sb(f"ht{i}", [128, D]) for i in range(8)]
    xn = [k.sb(f"xn{i}", [128, D], BF16) for i in range(2)]
    xT = [k.sb(f"xT{i}", [128, KC, 512], BF16) for i in range(2)]
    memT = k.sb("memT", [128, KC, MEM], BF16)
    KT = k.sb("KT", [128, KC, MEM], BF16)
    V = k.sb("V", [128, 2, D], BF16)
    QT = [k.sb(f"QT{i}", [128, KC, 512], BF16) for i in range(2)]
    Pm = [k.sb(f"Pm{i}", [128, 4, MEM], BF16) for i in range(3)]
    Pn = [k.sb(f"Pn{i}", [128, 4, MEM], BF16) for i in range(3)]
    PT = [k.sb(f"PT{i}", [128, 8, 128], BF16) for i in range(3)]
    OT = [k.sb(f"OT{i}", [128, KC, 128], BF16) for i in range(3)]
    tmp = [k.sb(f"tmp{i}", [128, 512]) for i in range(2)]
    junk = k.sb("junk", [128, D], BF16)
    ss = [k.sb(f"ss{i}", [128, 1]) for i in range(2)]
    ss2 = [k.sb(f"ss2{i}", [128, 2]) for i in range(2)]
    rstd = [k.sb(f"rstd{i}", [128, 1]) for i in range(2)]
    rstd2 = [k.sb(f"rstdb{i}", [128, 1]) for i in range(2)]
    mx = [k.sb(f"mx{i}", [128, 4]) for i in range(3)]
    sm = [k.sb(f"sm{i}", [128, 4]) for i in range(3)]
    psT = k.ps("psT", [128, D], BF16)
    psA = k.ps("psA", [128, 1024])
    psS = k.ps("psS", [128, 1024])
    psX = k.ps("psX", [128, 1024])
    for mt in range(2):
        k.dma('sp', ht[mt][:], mem[mt * 128:(mt + 1) * 128, :], w=[f'ht{mt}'])
        norm_T(k, ht[mt][:], f'ht{mt}', xn[mt][:], f'xn{mt}', memT[:, :, mt * 128:(mt + 1) * 128], 'memT', psT[:], 'psT',
               ss[mt][:], rstd[mt][:], junk[:], f'n{mt}')
    for cc in range(KC):
        pa = cc % 2
        for kc in range(KC):
            k.mm(psA[:, pa * 512:pa * 512 + MEM], Wk[:, kc, cc * 128:(cc + 1) * 128], memT[:, kc, :], kc == 0, kc == KC - 1,
                 [f'Wk{kc}', 'memT'], [f'psA{pa}'])
        k.cp('act' if cc % 2 else 'dve', KT[:, cc, :], psA[:, pa * 512:pa * 512 + MEM], [f'psA{pa}'], [f'KT{cc}'])
    for mt in range(2):
        for cg in range(2):
            for kc in range(KC):
                k.mm(psX[:, cg * 512:(cg + 1) * 512], memT[:, kc, mt * 128:(mt + 1) * 128], Wv[:, kc, cg * 512:(cg + 1) * 512],
                     kc == 0, kc == KC - 1, ['memT', f'Wv{kc}'], [f'psX{cg}'])
            k.cp('act' if cg else 'dve', V[:, mt, cg * 512:(cg + 1) * 512], psX[:, cg * 512:(cg + 1) * 512], [f'psX{cg}'], [f'V{mt}{cg}'])
    def tile(i):
        blk, tt = divmod(i, 4)
        xb = blk % 2
        b = i % 3
        rows = slice(i * 128, (i + 1) * 128)
        tsl = slice(tt * 128, (tt + 1) * 128)
        hb = xb * 4 + tt
        if tt == 0:
            for t2_ in range(4):
                i2 = blk * 4 + t2_
                b2 = i2 % 2
                hb2 = xb * 4 + t2_
                k.dma('sp', ht[hb2][:], hin[i2 * 128:(i2 + 1) * 128, :], w=[f'ht{hb2}'])
                norm_T(k, ht[hb2][:], f'ht{hb2}', xn[b2][:], f'xn{b2}', xT[xb][:, :, t2_ * 128:(t2_ + 1) * 128], f'xT{xb}', psT[:], 'psT',
                       ss[b2][:], rstd[b2][:], junk[:], f'n{b2}')
            for cc in range(KC):
                pa = cc % 2
                for kc in range(KC):
                    k.mm(psA[:, pa * 512:(pa + 1) * 512], Wq[:, kc, cc * 128:(cc + 1) * 128], xT[xb][:, kc, :], kc == 0, kc == KC - 1,
                         [f'Wq{kc}', f'xT{xb}'], [f'psA{pa}'])
                k.cp('act' if cc % 2 else 'dve', QT[xb][:, cc, :], psA[:, pa * 512:(pa + 1) * 512], [f'psA{pa}'], [f'QT{xb}{cc}'])
        for h in range(4):
            sb_ = h // 2
            for j in range(2):
                cc = 2 * h + j
                k.mm(psS[:, h * MEM:(h + 1) * MEM], QT[xb][:, cc, tsl], KT[:, cc, :], j == 0, j == 1,
                     [f'QT{xb}{cc}', f'KT{cc}'], [f'psS{sb_}'])
        k.P.op('dve', lambda e, b=b: e.tensor_reduce(out=mx[b][:], in_=psS[:].rearrange("p (h m) -> p h m", h=4),
                                                    axis=AX.X, op=ALU.max),
               reads=['psS0', 'psS1'], writes=[f'mx{b}'])
        k.ts('dve', mx[b][:], mx[b][:], -1.0 / 16.0, None, ALU.mult, None, [f'mx{b}'], [f'mx{b}'])
        for h in range(4):
            k.act(Pm[b][:, h, :], psS[:, h * MEM:(h + 1) * MEM], AF.Exp, [f'psS{h // 2}', f'mx{b}'], [f'Pm{b}', f'sm{b}'],
                  scale=1.0 / 16.0, bias=mx[b][:, h:h + 1], accum_out=sm[b][:, h:h + 1])
        k.recip(sm[b][:], sm[b][:], [f'sm{b}'], [f'sm{b}'])
        k.tt('dve', Pn[b][:], Pm[b][:], sm[b][:].unsqueeze(2).broadcast_to([128, 4, MEM]), ALU.mult,
             [f'Pm{b}', f'sm{b}'], [f'Pn{b}'])
        yield
        for h in range(4):
            for mt in range(2):
                k.tr(psT[:, (h * 2 + mt) * 128:(h * 2 + mt + 1) * 128], Pn[b][:, h, mt * 128:(mt + 1) * 128], k.identb[:],
                     [f'Pn{b}'], ['psT'])
        k.cp('act', PT[b][:], psT[:].rearrange("p (k t) -> p k t", k=8), ['psT'], [f'PT{b}'])
        for cc in range(KC):
            h = cc // 2
            pa = cc // 4
            for mt in range(2):
                k.mm(psA[:, cc * 128:(cc + 1) * 128], V[:, mt, cc * 128:(cc + 1) * 128], PT[b][:, h * 2 + mt, :],
                     mt == 0, mt == 1, [f'V{mt}{cc // 4}', f'PT{b}'], [f'psA{pa}'])
        k.cp('dve', OT[b][:, 0:4, :], psA[:, 0:512].rearrange("p (k t) -> p k t", k=4), ['psA0'], [f'OT{b}_0'])
        k.cp('act', OT[b][:, 4:8, :], psA[:, 512:1024].rearrange("p (k t) -> p k t", k=4), ['psA1'], [f'OT{b}_1'])
        yield
        for cg in range(2):
            for cc in range(KC):
                k.mm(psX[:, cg * 512:(cg + 1) * 512], OT[b][:, cc, :], Wo[:, cc, cg * 512:(cg + 1) * 512],
                     cc == 0, cc == KC - 1, [f'OT{b}_{cc // 4}', f'Wo{cc}'], [f'psX{cg}'])
        post_norm_res(k, [psX[:, 0:512], psX[:, 512:1024]], ['psX0', 'psX1'], ht[hb], f'ht{hb}',
                      g3bc, 'g3bc', [tmp[0][:], tmp[1][:]], ['tmp0', 'tmp1'], ss2[b % 2], rstd2[b % 2][:], junk, f'pn{b % 2}')
        k.dma('pool', hout[rows, :], ht[hb][:], r=[f'ht{hb}'], final=True)

    pipeline(tile, NTOK // 128)
    return k.finish()


def build_A2(NTOK, NC, fm, NF, k=None):
    k = k or K()
    NB = NTOK // 512
    x = k.din("x", [NTOK, D])
    gain = k.din("gain", [D])
    W = k.din("W", [D, NC])
    ident_d = k.din("ident", [128, 128])
    out = k.dout("out", [NTOK, NC])
    outT = k.dout("outT", [NF, NTOK])
    k.consts(ident_d)
    gc = k.gain_cols("gc", gain)
    Wb = k.load_weight("Wb", W, KC, NC, gcol=gc, gkey='gc', stage_cols=1408)
    cgs = [(c0, min(512, NC - c0)) for c0 in range(0, NC, 512)]
    xt = [k.sb(f"xt{i}", [128, D]) for i in range(2)]
    xn = [k.sb(f"xn{i}", [128, D], BF16) for i in range(2)]
    xT = [k.sb(f"xT{i}", [128, KC, 512], BF16) for i in range(2)]
    ot = [k.sb(f"ot{i}", [128, NC]) for i in range(2)]
    ft = [k.sb(f"ft{i}", [128, 512]) for i in range(2)]
    junk = k.sb("junk", [128, D], BF16)
    ss = [k.sb(f"ss{i}", [128, 1]) for i in range(2)]
    rstd = [k.sb(f"rstd{i}", [128, 1]) for i in range(2)]
    psT = k.ps("psT", [128, D], BF16)
    psO = [k.ps(f"psO{i}", [128, 512]) for i in range(4)]
    psF = [k.ps(f"psF{i}", [128, 512]) for i in range(2)]
    no = 0
    nf = 0
    for blk in range(NB):
        xb = blk % 2
        for tt in range(4):
            i = blk * 4 + tt
            b = i % 2
            k.dma('sp', xt[b][:], x[i * 128:(i + 1) * 128, :], w=[f'xt{b}'])
            norm_T(k, xt[b][:], f'xt{b}', xn[b][:], f'xn{b}', xT[xb][:, :, tt * 128:(tt + 1) * 128], f'xT{xb}', psT[:], 'psT',
                   ss[b][:], rstd[b][:], junk[:], f'n{b}')
        for tt in range(4):
            i = blk * 4 + tt
            b = i % 2
            for ci, (c0, cw) in enumerate(cgs):
                pb = no % 4
                no += 1
                for kc in range(KC):
                    k.mm(psO[pb][:, 0:cw], xT[xb][:, kc, tt * 128:(tt + 1) * 128], Wb[:, kc, c0:c0 + cw], kc == 0, kc == KC - 1,
                         [f'xT{xb}', f'Wb{kc}'], [f'psO{pb}'])
                k.cp('dve' if pb % 2 == 0 else 'act', ot[b][:, c0:c0 + cw], psO[pb][:, 0:cw], [f'psO{pb}'], [f'ot{b}_{pb % 2}'])
            k.dma('pool', out[i * 128:(i + 1) * 128, :], ot[b][:], r=[f'ot{b}_0', f'ot{b}_1'], final=True)
        for (c0, cw, r0) in fm:
            pf = nf % 2
            nf += 1
            for kc in range(KC):
                k.mm(psF[pf][0:cw, :], Wb[:, kc, c0:c0 + cw], xT[xb][:, kc, :], kc == 0, kc == KC - 1,
                     [f'Wb{kc}', f'xT{xb}'], [f'psF{pf}'])
            k.cp('dve' if pf == 0 else 'act', ft[pf][0:cw, :], psF[pf][0:cw, :], [f'psF{pf}'], [f'ft{pf}'])
            k.dma('pool', outT[r0:r0 + cw, blk * 512:(blk + 1) * 512], ft[pf][0:cw, :], r=[f'ft{pf}'], final=True)
    return k.finish()


def gen_GLA(L, k):
    NT = L // 128
    qT = k.din("qT", [128, L])
    kT = k.din("kT", [128, L])
    ktok = k.din("ktok", [L, 128])
    v = k.din("v", [L, 256])
    gate = k.din("gate", [L, 256])
    dlrT = k.din("dlrT", [16, L])
    w2 = k.din("w2", [16, 128])
    bdec = k.din("bdec", [1, 128])
    gn = k.din("gn", [256])
    triu_d = k.din("triu", [128, 128])
    trigt_d = k.din("trigt", [128, 128])
    oa = k.dout("oa", [L, 256])

    triu = k.sb("triu_s", [128, 128])
    trigt = k.sb("trigt_s", [128, 128])
    k.dma('sp', triu[:], triu_d, w=['triu'])
    k.dma('sp', trigt[:], trigt_d, w=['trigt'])
    w2s = k.sb("w2s", [16, 128])
    k.dma('sp', w2s[:], w2, w=['w2s'])
    bds = k.sb("bds", [1, 128])
    k.dma('sp', bds[:], bdec, w=['bds'])
    ones1 = k.sb("ones1", [1, 128])
    k.memset('dve', ones1[:], 1.0, ['ones1'])
    gnbc = k.bcast_row("gnbc", gn, 256)
    S = k.sb("S", [128, 128])
    k.memset('dve', S[:], 0.0, ['S'])
    rm = k.sb("rm", [128, 2])
    k.memset('dve', rm[:], 0.0, ['rm'])
    k.memset('dve', rm[0:64, 0:1], 0.125, ['rm'])
    k.memset('dve', rm[64:128, 1:2], 0.125, ['rm'])

    def ring(nm, shape, n, dt=F32):
        return [k.sb(f"{nm}{j}", shape, dt) for j in range(n)]
    qTt, kTt, kt, gt = ring("qTt", [128, 128], 8), ring("kTt", [128, 128], 8), ring("kt", [128, 128], 8), ring("gt", [128, 256], 8)
    vt = ring("vt", [128, 256], 11)
    dt_ = ring("dt", [16, 128], 3)
    la = ring("la", [128, 128], 4)
    sg = ring("sg", [128, 256], 16)
    EqT, EkT, Eks = ring("EqT", [128, 128], 7), ring("EkT", [128, 128], 3), ring("Eks", [128, 128], 3)
    qin, kin, kst = ring("qin", [128, 2, 128], 5), ring("kin", [128, 128], 3), ring("kst", [128, 128], 5)
    sc0, sc1 = ring("sc0_", [128, 128], 3), ring("sc1_", [128, 128], 3)
    osr = ring("osr", [128, 256], 6)
    osb = ring("osb", [128, 256], 3)
    ss, rs = ring("ss", [128, 2], 4), ring("rs", [128, 2], 5)
    ot = ring("ot", [128, 256], 3)
    junk = k.sb("junk", [128, 128])
    psZ = [k.ps(f"psZ{j}", [128, 512]) for j in range(2)]
    psA = [k.ps(f"psA{j}", [128, 512]) for j in range(2)]
    psB = [k.ps(f"psB{j}", [128, 512]) for j in range(2)]
    psC = [k.ps(f"psC{j}", [128, 512]) for j in range(2)]

    def tile(i):
        rows = slice(i * 128, (i + 1) * 128)
        R = lambda lst: (lst[i % len(lst)], f'{lst[0].name if hasattr(lst[0], "name") else id(lst)}_{i % len(lst)}')
        def T(lst, nm):
            j = i % len(lst)
            return lst[j], f'{nm}{j}'
        q_, kq = T(qTt, 'qTt'); kT_, kkT = T(kTt, 'kTt'); kt_, kkt = T(kt, 'kt'); v_, kv = T(vt, 'vt'); g_, kg = T(gt, 'gt')
        d_, kd = T(dt_, 'dt'); la_, kla = T(la, 'la'); sg_, ksg = T(sg, 'sg')
        Eq, kEq = T(EqT, 'EqT'); Ek, kEk = T(EkT, 'EkT'); Es, kEs = T(Eks, 'Eks')
        qi, kqi = T(qin, 'qin'); ki, kki = T(kin, 'kin'); ks, kks = T(kst, 'kst')
        scs = [T(sc0, 'sc0_'), T(sc1, 'sc1_')]
        orw, korw = T(osr, 'osr'); ob_, kob = T(osb, 'osb'); ss_, kss = T(ss, 'ss'); rs_, krs = T(rs, 'rs'); ot_, kot = T(ot, 'ot')
        pz, kpz = psZ[i % 2], f'psZ{i % 2}'
        pa, kpa = psA[i % 2], f'psA{i % 2}'
        pb, kpb = psB[i % 2], f'psB{i % 2}'
        pc, kpc = psC[i % 2], f'psC{i % 2}'
        k.dma('sp', q_[:], qT[:, rows], w=[kq])
        k.dma('sp', kT_[:], kT[:, rows], w=[kkT])
        k.dma('sp', kt_[:], ktok[rows, :], w=[kkt])
        k.dma('sp', v_[:], v[rows, :], w=[kv])
        k.dma('sp', g_[:], gate[rows, :], w=[kg])
        k.dma('sp', d_[:], dlrT[:, rows], w=[kd])
        yield
        k.mm(pz[:, 0:128], d_[:], w2s[:], True, False, [kd, 'w2s'], [kpz])
        k.mm(pz[:, 0:128], ones1[:], bds[:], False, True, ['ones1', 'bds'], [kpz])
        yield
        k.act(la_[:], pz[:, 0:128], AF.Exp, [kpz], [kla], scale=-1.0)
        k.act(la_[:], la_[:], AF.Ln, [kla], [kla], bias=1.0)
        k.act(sg_[:], g_[:], AF.Exp, [kg], [ksg], scale=-1.0)
        yield
        k.ts('dve', la_[:], la_[:], -1.0 / 16.0, None, ALU.mult, None, [kla], [kla])
        k.ts('dve', sg_[:], sg_[:], 1.0, None, ALU.add, None, [ksg], [ksg])
        k.recip(sg_[:], sg_[:], [ksg], [ksg])
        yield
        k.mm(pa[:, 0:128], la_[:], triu[:], True, True, [kla, 'triu'], [kpa])
        k.mm(pa[:, 128:256], trigt[:], la_[:], True, True, [kla, 'trigt'], [kpa])
        yield
        k.act(Eq[:], pa[:, 0:128], AF.Exp, [kpa], [kEq])
        k.act(Ek[:], pa[:, 0:128], AF.Exp, [kpa], [kEk], scale=-1.0)
        k.act(Es[:], pa[:, 128:256], AF.Exp, [kpa], [kEs])
        yield
        for h in range(2):
            k.stt(qi[:, h, :], q_[:], rm[:, h:h + 1], Eq[:], ALU.mult, ALU.mult, [kq, kEq, 'rm'], [kqi])
        k.tt('pool', ki[:], kT_[:], Ek[:], ALU.mult, [kkT, kEk], [kki])
        k.tt('pool', ks[:], kt_[:], Es[:], ALU.mult, [kkt, kEs], [kks])
        k.tt('pool', sg_[:], sg_[:], g_[:], ALU.mult, [ksg, kg], [ksg])
        yield
        for h in range(2):
            hp = slice(h * 64, (h + 1) * 64)
            k.mm(pb[:, h * 128:(h + 1) * 128], ki[:], qi[:, h, :], True, True, [kki, kqi], [kpb])
        yield
        for h in range(2):
            k.tt('dve', scs[h][0][:], pb[:, h * 128:(h + 1) * 128], triu[:], ALU.mult, [kpb, 'triu'], [scs[h][1]])
        yield
        for h in range(2):
            hp = slice(h * 64, (h + 1) * 64)
            k.mm(pc[:, h * 128:(h + 1) * 128], scs[h][0][:], v_[:, h * 128:(h + 1) * 128], True, False, [scs[h][1], kv], [kpc])
            k.mm(pc[:, h * 128:(h + 1) * 128], qi[:, h, :], S[:], False, True, [kqi, 'S'], [kpc])
        k.mm(pc[:, 256:512], ks[:], v_[:], True, True, [kks, kv], [kpc])
        yield
        for h in range(2):
            hp = slice(h * 64, (h + 1) * 64)
            k.stt(S[hp, :], S[hp, :], Eq[hp, 127:128], pc[hp, 256 + h * 128:256 + (h + 1) * 128], ALU.mult, ALU.add,
                  ['S', kEq, kpc], ['S'])
        k.cp('act', orw[:], pc[:, 0:256], [kpc], [korw])
        yield
        for h in range(2):
            k.act(junk[:], orw[:, h * 128:(h + 1) * 128], AF.Square, [korw], ['junk', kss], accum_out=ss_[:, h:h + 1])
        yield
        k.ts('dve', rs_[:], ss_[:], 1.0 / 128.0, EPS, ALU.mult, ALU.add, [kss], [krs])
        yield
        k.act(rs_[:], rs_[:], AF.Ln, [krs], [krs])
        k.act(rs_[:], rs_[:], AF.Exp, [krs], [krs], scale=-0.5)
        yield
        for h in range(2):
            hs = slice(h * 128, (h + 1) * 128)
            k.stt(ob_[:, hs], orw[:, hs], rs_[:, h:h + 1], gnbc[:, hs], ALU.mult, ALU.mult, [korw, krs, 'gnbc'], [kob])
        yield
        k.tt('pool', ot_[:], ob_[:], sg_[:], ALU.mult, [kob, ksg], [kot])
        k.dma('pool', oa[rows, :], ot_[:], r=[kot], final=True)

    yield from pipeline_gen(tile, NT)


def build_GLA(L, k=None):
    k = k or K()
    for _ in gen_GLA(L, k):
        pass
    return k.finish()


TWO_PI = 2.0 * math.pi
C1 = 6.28125
C2 = TWO_PI - 6.28125
PI_LO = 3.1415925


def range_sincos(k, x, xkey, shape, s_out, c_out, skey, ckey, pfx):
    if not hasattr(k, 'rr_cache'):
        k.rr_cache = {}
    if pfx not in k.rr_cache:
        k.rr_cache[pfx] = (k.sb(pfx + "kf", shape), k.sb(pfx + "ki", shape, I32), k.sb(pfx + "r", shape), k.sb(pfx + "m", shape))
    kf, ki, r, m = k.rr_cache[pfx]
    a = lambda t: t[:]
    K1, K2, K3, K4 = pfx + 'kf', pfx + 'ki', pfx + 'r', pfx + 'm'
    k.ts('dve', a(kf), x, 1.0 / TWO_PI, None, ALU.mult, None, [xkey], [K1])
    k.cp('dve', a(ki), a(kf), [K1], [K2])
    k.cp('dve', a(kf), a(ki), [K2], [K1])
    k.stt(a(r), a(kf), -C1, x, ALU.mult, ALU.add, [K1, xkey], [K3])
    k.stt(a(r), a(kf), -C2, a(r), ALU.mult, ALU.add, [K1, K3], [K3])
    k.ts('dve', a(m), a(r), math.pi, -TWO_PI, ALU.is_gt, ALU.mult, [K3], [K4])
    k.tt('dve', a(r), a(r), a(m), ALU.add, [K3, K4], [K3])
    k.ts('dve', a(m), a(r), -math.pi, TWO_PI, ALU.is_lt, ALU.mult, [K3], [K4])
    k.tt('dve', a(r), a(r), a(m), ALU.add, [K3, K4], [K3])
    k.ts('dve', a(kf), a(r), PI_LO, -PI_LO, ALU.min, ALU.max, [K3], [K1])
    k.act(s_out, a(kf), AF.Sin, [K1], [skey])
    k.ts('dve', a(r), a(r), math.pi / 2, None, ALU.add, None, [K3], [K3])
    k.ts('dve', a(m), a(r), math.pi, -TWO_PI, ALU.is_gt, ALU.mult, [K3], [K4])
    k.tt('dve', a(r), a(r), a(m), ALU.add, [K3, K4], [K3])
    k.ts('dve', a(kf), a(r), PI_LO, -PI_LO, ALU.min, ALU.max, [K3], [K1])
    k.act(c_out, a(kf), AF.Sin, [K1], [ckey])


def gen_S5(L, k):
    NT = L // 128
    NS = 1024
    uT = k.din("uT", [256, L])
    u = k.din("u", [L, 256])
    lam_re = k.din("lam_re", [NS])
    lam_im = k.din("lam_im", [NS])
    lstep = k.din("lstep", [NS])
    Bre = k.din("Bre", [2, 128, 512])
    Bim = k.din("Bim", [2, 128, 512])
    Cre = k.din("Cre", [8, 128, 32])
    Cim = k.din("Cim", [8, 128, 32])
    dsk = k.din("dsk", [256])
    triu_d = k.din("triu", [128, 128])
    iop_d = k.din("iota_p", [128, 1])
    iof_d = k.din("iota_f", [128, 128])
    y = k.dout("y", [L, 256])

    k.push_scope([("triu_s", [128, 128], F32), ("dbc", [128, 256], F32), ("BBr", [128, 2, 512], F32), ("BBi", [128, 2, 512], F32),
                  ("Pr", [128, NS], F32), ("Pi", [128, NS], F32), ("Qr", [128, 8, 128], F32), ("Qi", [128, 8, 128], F32),
                  ("L128r", [128, 8], F32), ("L128i", [128, 8], F32), ("Cr", [128, 8, 32], F32), ("nCi", [128, 8, 32], F32),
                  ("car_r", [128, 8], F32), ("car_i", [128, 8], F32)])
    triu = k.sb("triu_s", [128, 128])
    k.dma('sp', triu[:], triu_d, w=['triu'])
    iop = k.sb("iop", [128, 1])
    k.dma('sp', iop[:], iop_d, w=['iop'])
    negp = k.sb("negp", [128, 1])
    k.ts('dve', negp[:], iop[:], -1.0, None, ALU.mult, None, ['iop'], ['negp'])
    iof = k.sb("iof", [128, 128])
    k.dma('sp', iof[:], iof_d, w=['iof'])
    dbc = k.bcast_row("dbc", dsk, 256)
    R = [128, NS]
    lr = k.bcast_row("lr", lam_re, NS)
    li = k.bcast_row("li", lam_im, NS)
    dl = k.bcast_row("dl", lstep, NS)
    k.ts('dve', lr[:], lr[:], -1e-4, None, ALU.min, None, ['lr'], ['lr'])
    k.act(dl[:], dl[:], AF.Exp, ['dl'], ['dl'])
    a_ = k.sb("a_", R)
    th = k.sb("th", R)
    k.tt('dve', a_[:], lr[:], dl[:], ALU.mult, ['lr', 'dl'], ['a_'])
    k.tt('dve', th[:], li[:], dl[:], ALU.mult, ['li', 'dl'], ['th'])
    sn = k.sb("sn", R)
    cs = k.sb("cs", R)
    range_sincos(k, th[:], 'th', R, sn[:], cs[:], 'sn', 'cs', 'rr_')
    ea = k.sb("ea", R)
    k.act(ea[:], a_[:], AF.Exp, ['a_'], ['ea'])
    nr = k.sb("nr", R)
    ni = k.sb("ni", R)
    k.tt('dve', nr[:], ea[:], cs[:], ALU.mult, ['ea', 'cs'], ['nr'])
    k.ts('dve', nr[:], nr[:], -1.0, None, ALU.add, None, ['nr'], ['nr'])
    k.tt('dve', ni[:], ea[:], sn[:], ALU.mult, ['ea', 'sn'], ['ni'])
    den = k.sb("den", R)
    t0 = k.sb("t0", R)
    k.tt('dve', den[:], lr[:], lr[:], ALU.mult, ['lr'], ['den'])
    k.tt('dve', t0[:], li[:], li[:], ALU.mult, ['li'], ['t0'])
    k.tt('dve', den[:], den[:], t0[:], ALU.add, ['den', 't0'], ['den'])
    k.recip(den[:], den[:], ['den'], ['den'])
    gr = k.sb("gr", R)
    gi = k.sb("gi", R)
    k.tt('dve', gr[:], nr[:], lr[:], ALU.mult, ['nr', 'lr'], ['gr'])
    k.tt('dve', t0[:], ni[:], li[:], ALU.mult, ['ni', 'li'], ['t0'])
    k.tt('dve', gr[:], gr[:], t0[:], ALU.add, ['gr', 't0'], ['gr'])
    k.tt('dve', gr[:], gr[:], den[:], ALU.mult, ['gr', 'den'], ['gr'])
    k.tt('dve', gi[:], ni[:], lr[:], ALU.mult, ['ni', 'lr'], ['gi'])
    k.tt('dve', t0[:], nr[:], li[:], ALU.mult, ['nr', 'li'], ['t0'])
    k.tt('dve', gi[:], gi[:], t0[:], ALU.subtract, ['gi', 't0'], ['gi'])
    k.tt('dve', gi[:], gi[:], den[:], ALU.mult, ['gi', 'den'], ['gi'])
    Br = k.sb("Br", [128, 2, 512])
    Bi = k.sb("Bi", [128, 2, 512])
    BBr = k.sb("BBr", [128, 2, 512])
    BBi = k.sb("BBi", [128, 2, 512])
    for hc in range(2):
        k.dma('sp', Br[:, hc, :], Bre[hc], w=[f'Br{hc}'])
        k.dma('sp', Bi[:, hc, :], Bim[hc], w=[f'Bi{hc}'])
    grv = gr[:].rearrange("p (h n) -> p h n", h=2)
    giv = gi[:].rearrange("p (h n) -> p h n", h=2)
    t0v = t0[:].rearrange("p (h n) -> p h n", h=2)
    BK = ['Br0', 'Br1', 'Bi0', 'Bi1']
    k.tt('dve', BBr[:], grv, Br[:], ALU.mult, ['gr'] + BK, ['BBr'])
    k.tt('dve', t0v, giv, Bi[:], ALU.mult, ['gi'] + BK, ['t0'])
    k.tt('dve', BBr[:], BBr[:], t0v, ALU.subtract, ['BBr', 't0'], ['BBr'])
    k.tt('dve', BBi[:], grv, Bi[:], ALU.mult, ['gr'] + BK, ['BBi'])
    k.tt('dve', t0v, giv, Br[:], ALU.mult, ['gi'] + BK, ['t0'])
    k.tt('dve', BBi[:], BBi[:], t0v, ALU.add, ['BBi', 't0'], ['BBi'])
    ang = k.sb("ang", R)
    k.ts('dve', ang[:], th[:], iop[:, 0:1], None, ALU.mult, None, ['th', 'iop'], ['ang'])
    Pr = k.sb("Pr", R)
    Pi = k.sb("Pi", R)
    range_sincos(k, ang[:], 'ang', R, sn[:], cs[:], 'sn', 'cs', 'rr_')
    k.act(ea[:], a_[:], AF.Exp, ['a_', 'negp'], ['ea'], scale=negp[:, 0:1])
    k.tt('dve', Pr[:], ea[:], cs[:], ALU.mult, ['ea', 'cs'], ['Pr'])
    k.stt(Pi[:], ea[:], -1.0, sn[:], ALU.mult, ALU.mult, ['ea', 'sn'], ['Pi'])
    Cs = [128, 8]
    lrc = k.sb("lrc", Cs)
    lic = k.sb("lic", Cs)
    dlc = k.sb("dlc", Cs)
    cv = lambda d: d.rearrange("(blk p) -> p blk", p=128)
    k.dma('sp', lrc[:], cv(lam_re), w=['lrc'], allow_slow_non_contiguous=True)
    k.dma('sp', lic[:], cv(lam_im), w=['lic'], allow_slow_non_contiguous=True)
    k.dma('sp', dlc[:], cv(lstep), w=['dlc'], allow_slow_non_contiguous=True)
    k.ts('dve', lrc[:], lrc[:], -1e-4, None, ALU.min, None, ['lrc'], ['lrc'])
    k.act(dlc[:], dlc[:], AF.Exp, ['dlc'], ['dlc'])
    ac = k.sb("ac", Cs)
    thc = k.sb("thc", Cs)
    k.tt('dve', ac[:], lrc[:], dlc[:], ALU.mult, ['lrc', 'dlc'], ['ac'])
    k.tt('dve', thc[:], lic[:], dlc[:], ALU.mult, ['lic', 'dlc'], ['thc'])
    Qr = k.sb("Qr", [128, 8, 128])
    Qi = k.sb("Qi", [128, 8, 128])
    angv = ang[:].rearrange("p (b t) -> p b t", b=8)
    eav = ea[:].rearrange("p (b t) -> p b t", b=8)
    for blk in range(8):
        k.ts('dve', angv[:, blk, :], iof[:], thc[:, blk:blk + 1], None, ALU.mult, None, ['iof', 'thc'], ['ang'])
    range_sincos(k, ang[:], 'ang', R, sn[:], cs[:], 'sn', 'cs', 'rr_')
    for blk in range(8):
        k.act(eav[:, blk, :], iof[:], AF.Exp, ['iof', 'ac'], ['ea'], scale=ac[:, blk:blk + 1])
    k.tt('dve', Qr[:].rearrange("p b t -> p (b t)"), ea[:], cs[:], ALU.mult, ['ea', 'cs'], ['Qr'])
    k.tt('dve', Qi[:].rearrange("p b t -> p (b t)"), ea[:], sn[:], ALU.mult, ['ea', 'sn'], ['Qi'])
    a128 = k.sb("a128", Cs)
    s128 = k.sb("s128", Cs)
    c128 = k.sb("c128", Cs)
    L128r = k.sb("L128r", Cs)
    L128i = k.sb("L128i", Cs)
    k.ts('dve', a128[:], thc[:], 128.0, None, ALU.mult, None, ['thc'], ['a128'])
    range_sincos(k, a128[:], 'a128', Cs, s128[:], c128[:], 's128', 'c128', 'rc_')
    k.act(a128[:], ac[:], AF.Exp, ['ac', 's128', 'c128'], ['a128'], scale=128.0)
    k.tt('dve', L128r[:], a128[:], c128[:], ALU.mult, ['a128', 'c128'], ['L128r'])
    k.tt('dve', L128i[:], a128[:], s128[:], ALU.mult, ['a128', 's128'], ['L128i'])
    Cr = k.sb("Cr", [128, 8, 32])
    nCi = k.sb("nCi", [128, 8, 32])
    k.dma('sp', Cr[:], Cre.rearrange("b p c -> p b c"), w=['Cr'])
    k.dma('sp', nCi[:], Cim.rearrange("b p c -> p b c"), w=['nCi'])
    k.ts('dve', nCi[:], nCi[:], -1.0, None, ALU.mult, None, ['nCi'], ['nCi'])
    car_r = k.sb("car_r", Cs)
    car_i = k.sb("car_i", Cs)
    k.memset('dve', car_r[:], 0.0, ['car_r0', 'car_r1'])
    k.memset('dve', car_i[:], 0.0, ['car_i0', 'car_i1'])
    k.pop_scope()
    if hasattr(k, 'rr_cache'):
        del k.rr_cache
    uTt = [k.sb(f"uTt{i}", [128, 2, 128]) for i in range(2)]
    ut = [k.sb(f"ut{i}", [128, 256]) for i in range(4)]
    yo = [k.sb(f"yo{i}", [128, 256]) for i in range(2)]
    def T4(nm):
        return [[k.sb(f"{nm}{p}{h}", [128, 512]) for h in range(2)] for p in range(2)]
    m1, m2, m3, m4, Xtr, Xti = T4("m1_"), T4("m2_"), T4("m3_"), T4("m4_"), T4("Xtr"), T4("Xti")
    def T3(nm):
        return [[k.sb(f"{nm}{p}{h}", [128, 4, 128]) for h in range(2)] for p in range(2)]
    Gr, Gi, Hr, Hi = T3("Gr"), T3("Gi"), T3("Hr"), T3("Hi")
    cc1 = [k.sb(f"cc1_{h}", [128, 4]) for h in range(2)]
    cc2 = [k.sb(f"cc2_{h}", [128, 4]) for h in range(2)]
    psX = [[k.ps(f"psX{h}{c}", [128, 512]) for c in range(2)] for h in range(2)]
    psY = k.ps("psY", [128, 512])
    fl = lambda t: t[:].rearrange("p b t -> p (b t)")

    def tile(i):
        b = i % 2
        rows = slice(i * 128, (i + 1) * 128)
        K_ = lambda nm, hc: f'{nm}{b}{hc}'
        for hc in range(2):
            k.dma('sp', uTt[b][:, hc, :], uT[hc * 128:(hc + 1) * 128, rows], w=[f'uTt{b}{hc}'])
        b4 = i % 4
        k.dma('sp', ut[b4][:], u[rows, :], w=[f'ut{b4}'])
        for hc in range(2):
            k.mm(psX[hc][0][:], uTt[b][:, hc, :], BBr[:, hc, :], True, True, [f'uTt{b}{hc}', 'BBr'], [f'psX{hc}0'])
            k.mm(psX[hc][1][:], uTt[b][:, hc, :], BBi[:, hc, :], True, True, [f'uTt{b}{hc}', 'BBi'], [f'psX{hc}1'])
        for hc in range(2):
            cs_ = slice(hc * 512, (hc + 1) * 512)
            k.tt('dve', m1[b][hc][:], psX[hc][0][:], Pr[:, cs_], ALU.mult, [f'psX{hc}0', 'Pr'], [K_('m1', hc)])
            k.tt('dve', m2[b][hc][:], psX[hc][1][:], Pi[:, cs_], ALU.mult, [f'psX{hc}1', 'Pi'], [K_('m2', hc)])
            k.tt('pool', Xtr[b][hc][:], m1[b][hc][:], m2[b][hc][:], ALU.subtract, [K_('m1', hc), K_('m2', hc)], [K_('Xtr', hc)])
            k.tt('dve', m3[b][hc][:], psX[hc][0][:], Pi[:, cs_], ALU.mult, [f'psX{hc}0', 'Pi'], [K_('m3', hc)])
            k.tt('dve', m4[b][hc][:], psX[hc][1][:], Pr[:, cs_], ALU.mult, [f'psX{hc}1', 'Pr'], [K_('m4', hc)])
            k.tt('pool', Xti[b][hc][:], m3[b][hc][:], m4[b][hc][:], ALU.add, [K_('m3', hc), K_('m4', hc)], [K_('Xti', hc)])
        yield
        for hc in range(2):
            for nb in range(4):
                ns = slice(nb * 128, (nb + 1) * 128)
                k.mm(psX[hc][0][:, ns], Xtr[b][hc][:, ns], triu[:], True, True, [K_('Xtr', hc), 'triu'], [f'psX{hc}0'])
                k.mm(psX[hc][1][:, ns], Xti[b][hc][:, ns], triu[:], True, True, [K_('Xti', hc), 'triu'], [f'psX{hc}1'])
        for hc in range(2):
            bs = slice(hc * 4, (hc + 1) * 4)
            k.tt('dve', Gr[b][hc][:], psX[hc][0][:].rearrange("p (b t) -> p b t", b=4),
                 car_r[:, bs].unsqueeze(2).broadcast_to([128, 4, 128]), ALU.add, [f'psX{hc}0', f'car_r{hc}'], [K_('Gr', hc)])
            k.tt('dve', Gi[b][hc][:], psX[hc][1][:].rearrange("p (b t) -> p b t", b=4),
                 car_i[:, bs].unsqueeze(2).broadcast_to([128, 4, 128]), ALU.add, [f'psX{hc}1', f'car_i{hc}'], [K_('Gi', hc)])
            gr127 = Gr[b][hc][:, :, 127]
            gi127 = Gi[b][hc][:, :, 127]
            CK = [f'cc1{hc}', f'cc2{hc}']
            k.tt('pool', cc1[hc][:], L128r[:, bs], gr127, ALU.mult, ['L128r', K_('Gr', hc)], [CK[0]])
            k.tt('pool', cc2[hc][:], L128i[:, bs], gi127, ALU.mult, ['L128i', K_('Gi', hc)], [CK[1]])
            k.tt('pool', car_r[:, bs], cc1[hc][:], cc2[hc][:], ALU.subtract, CK, [f'car_r{hc}'])
            k.tt('pool', cc1[hc][:], L128r[:, bs], gi127, ALU.mult, ['L128r', K_('Gi', hc)], [CK[0]])
            k.tt('pool', cc2[hc][:], L128i[:, bs], gr127, ALU.mult, ['L128i', K_('Gr', hc)], [CK[1]])
            k.tt('pool', car_i[:, bs], cc1[hc][:], cc2[hc][:], ALU.add, CK, [f'car_i{hc}'])
        yield
        for hc in range(2):
            bs = slice(hc * 4, (hc + 1) * 4)
            qr = Qr[:, bs, :].rearrange("p b t -> p (b t)")
            qi = Qi[:, bs, :].rearrange("p b t -> p (b t)")
            k.tt('dve', m1[b][hc][:], fl(Gr[b][hc]), qr, ALU.mult, [K_('Gr', hc), 'Qr'], [K_('m1', hc)])
            k.tt('pool', m2[b][hc][:], fl(Gi[b][hc]), qi, ALU.mult, [K_('Gi', hc), 'Qi'], [K_('m2', hc)])
            k.tt('dve', fl(Hr[b][hc]), m1[b][hc][:], m2[b][hc][:], ALU.subtract, [K_('m1', hc), K_('m2', hc)], [K_('Hr', hc)])
            k.tt('dve', m3[b][hc][:], fl(Gi[b][hc]), qr, ALU.mult, [K_('Gi', hc), 'Qr'], [K_('m3', hc)])
            k.tt('pool', m4[b][hc][:], fl(Gr[b][hc]), qi, ALU.mult, [K_('Gr', hc), 'Qi'], [K_('m4', hc)])
            k.tt('dve', fl(Hi[b][hc]), m3[b][hc][:], m4[b][hc][:], ALU.add, [K_('m3', hc), K_('m4', hc)], [K_('Hi', hc)])
        yield
        for hc in range(2):
            for nb in range(4):
                blk = hc * 4 + nb
                k.mm(psY[:, blk * 32:(blk + 1) * 32], Hr[b][hc][:, nb, :], Cr[:, blk, :], True, False, [K_('Hr', hc), 'Cr'], ['psY'])
                k.mm(psY[:, blk * 32:(blk + 1) * 32], Hi[b][hc][:, nb, :], nCi[:, blk, :], False, True, [K_('Hi', hc), 'nCi'], ['psY'])
        k.tt('pool', yo[b][:], ut[b4][:], dbc[:], ALU.mult, [f'ut{b4}', 'dbc'], [f'yo{b}'])
        k.tt('dve', yo[b][:], yo[b][:], psY[:, 0:256], ALU.add, [f'yo{b}', 'psY'], [f'yo{b}'])
        k.dma('pool', y[rows, :], yo[b][:], r=[f'yo{b}'], final=True)

    yield from pipeline_gen(tile, NT)


def build_S5(L, k=None):
    k = k or K()
    for _ in gen_S5(L, k):
        pass
    return k.finish()


def s5_host_inputs(s, proj_u, prm):
    gs = slice(16 * s, 16 * s + 16)
    cs = slice(256 * s, 256 * s + 256)
    uc = np.ascontiguousarray(proj_u[:, cs])
    Bre = np.zeros((2, 128, 512), np.float32)
    Bim = np.zeros((2, 128, 512), np.float32)
    Cre = np.zeros((8, 128, 32), np.float32)
    Cim = np.zeros((8, 128, 32), np.float32)
    b_re, b_im = prm['s5_b_re'][gs], prm['s5_b_im'][gs]
    c_re, c_im = prm['s5_c_re'][gs], prm['s5_c_im'][gs]
    for g in range(16):
        hc, gl = g // 8, g % 8
        Bre[hc, gl * 16:(gl + 1) * 16, gl * 64:(gl + 1) * 64] = b_re[g].T
        Bim[hc, gl * 16:(gl + 1) * 16, gl * 64:(gl + 1) * 64] = b_im[g].T
        blk, g2 = g // 2, g % 2
        Cre[blk, g2 * 64:(g2 + 1) * 64, g2 * 16:(g2 + 1) * 16] = c_re[g].T
        Cim[blk, g2 * 64:(g2 + 1) * 64, g2 * 16:(g2 + 1) * 16] = c_im[g].T
    return dict(uT=np.ascontiguousarray(uc.T), u=uc,
                lam_re=np.ascontiguousarray(prm['s5_lambda_re'][gs].reshape(-1)),
                lam_im=np.ascontiguousarray(prm['s5_lambda_im'][gs].reshape(-1)),
                lstep=np.ascontiguousarray(np.repeat(prm['s5_log_step'][gs], 64)),
                Bre=Bre, Bim=Bim, Cre=Cre, Cim=Cim, dsk=np.ascontiguousarray(prm['s5_d'][cs]),
                triu=np.triu(np.ones((128, 128), np.float32)),
                iota_p=np.arange(128, dtype=np.float32).reshape(128, 1),
                iota_f=np.tile(np.arange(128, dtype=np.float32)[None], (128, 1)))


GELU_C = 1.5957691216057308


def gen_LRU(L, k):
    TT = 512
    NCH = L // TT
    xbT = k.din("xbT", [256, L])
    gateT = k.din("gateT", [256, L])
    cw_d = k.din("cw", [128, 2, 4])
    cb_d = k.din("cb", [128, 2])
    Wa_d = k.din("Wa", [2, 128, 128])
    Wx_d = k.din("Wx", [2, 128, 128])
    ba_d = k.din("ba", [128, 2])
    bx_d = k.din("bx", [128, 2])
    lam_d = k.din("lam", [128, 2])
    odT = k.dout("odT", [256, L])
    cw = k.sb("cw_s", [128, 2, 4])
    cb = k.sb("cb_s", [128, 2])
    Wa = k.sb("Wa_s", [128, 2, 128])
    Wx = k.sb("Wx_s", [128, 2, 128])
    ba = k.sb("ba_s", [128, 2])
    bx = k.sb("bx_s", [128, 2])
    c8 = k.sb("c8", [128, 2])
    k.dma('sp', cw[:], cw_d, w=['cw'])
    k.dma('sp', cb[:], cb_d, w=['cb'])
    k.dma('sp', Wa[:], Wa_d.rearrange("b p n -> p b n"), w=['Wa'])
    k.dma('sp', Wx[:], Wx_d.rearrange("b p n -> p b n"), w=['Wx'])
    k.dma('sp', ba[:], ba_d, w=['ba'])
    k.dma('sp', bx[:], bx_d, w=['bx'])
    k.dma('sp', c8[:], lam_d, w=['c8'])
    k.act(c8[:], c8[:], AF.Exp, ['c8'], ['c8'], scale=-1.0)
    k.act(c8[:], c8[:], AF.Ln, ['c8'], ['c8'], bias=1.0)
    k.ts('dve', c8[:], c8[:], -8.0, None, ALU.mult, None, ['c8'], ['c8'])
    hlast = k.sb("hlast", [128, 2])
    k.memset('dve', hlast[:], 0.0, ['hlast0', 'hlast1'])
    xh = [k.sb(f"xh{i}", [128, TT + 3]) for i in range(2)]
    gt = [k.sb(f"gt{i}", [128, TT]) for i in range(2)]
    xc = k.sb("xc", [128, TT])
    r = k.sb("r", [128, TT])
    ig = k.sb("ig", [128, TT])
    a = k.sb("a", [128, TT])
    a2 = k.sb("a2", [128, TT])
    bt = k.sb("bt", [128, TT])
    h = k.sb("h", [128, TT])
    g2 = k.sb("g2", [128, TT])
    ge = k.sb("ge", [128, TT])
    ot = [k.sb(f"ot{i}", [128, TT]) for i in range(2)]
    psR = k.ps("psR", [128, TT])
    psI = k.ps("psI", [128, TT])
    n = 0
    for c in range(NCH):
        for pb in range(2):
            b = n % 2
            n += 1
            prow = slice(pb * 128, (pb + 1) * 128)
            if c == 0:
                k.memset('pool', xh[b][:, 0:3], 0.0, [f'xh{b}h'])
                k.dma('sp', xh[b][:, 3:TT + 3], xbT[prow, 0:TT], w=[f'xh{b}'])
            else:
                k.dma('sp', xh[b][:, 0:TT + 3], xbT[prow, c * TT - 3:(c + 1) * TT], w=[f'xh{b}', f'xh{b}h'])
            k.dma('sp', gt[b][:], gateT[prow, c * TT:(c + 1) * TT], w=[f'gt{b}'])
            xk = [f'xh{b}', f'xh{b}h']
            k.ts('dve', xc[:], xh[b][:, 3:TT + 3], cw[:, pb, 3:4], cb[:, pb:pb + 1], ALU.mult, ALU.add, xk + ['cw', 'cb'], ['xc'])
            for j in (2, 1, 0):
                k.stt(xc[:], xh[b][:, j:j + TT], cw[:, pb, j:j + 1], xc[:], ALU.mult, ALU.add, xk + ['cw', 'xc'], ['xc'])
            k.mm(psR[:], Wa[:, pb, :], xc[:], True, True, ['Wa', 'xc'], ['psR'])
            k.mm(psI[:], Wx[:, pb, :], xc[:], True, True, ['Wx', 'xc'], ['psI'])
            k.act(r[:], psR[:], AF.Sigmoid, ['psR', 'ba'], ['r'], bias=ba[:, pb:pb + 1])
            k.act(ig[:], psI[:], AF.Sigmoid, ['psI', 'bx'], ['ig'], bias=bx[:, pb:pb + 1])
            k.act(a[:], r[:], AF.Exp, ['r', 'c8'], ['a'], scale=c8[:, pb:pb + 1])
            k.act(a2[:], a[:], AF.Square, ['a'], ['a2'])
            k.act(a2[:], a2[:], AF.Sqrt, ['a2'], ['a2'], scale=-1.0, bias=1.0)
            k.tt('pool', bt[:], ig[:], xc[:], ALU.mult, ['ig', 'xc'], ['bt'])
            k.tt('pool', bt[:], bt[:], a2[:], ALU.mult, ['bt', 'a2'], ['bt'])
            k.P.op('dve', lambda e, pb=pb: e.tensor_tensor_scan(out=h[:], data0=a[:], data1=bt[:], initial=hlast[:, pb:pb + 1],
                                                                op0=ALU.mult, op1=ALU.add),
                   reads=['a', 'bt', f'hlast{pb}'], writes=['h'])
            k.cp('dve', hlast[:, pb:pb + 1], h[:, TT - 1:TT], ['h'], [f'hlast{pb}'])
            k.act(g2[:], gt[b][:], AF.Square, [f'gt{b}'], ['g2'])
            k.act(g2[:], g2[:], AF.Copy, ['g2'], ['g2'], scale=0.044715, bias=1.0)
            k.tt('pool', g2[:], g2[:], gt[b][:], ALU.mult, ['g2', f'gt{b}'], ['g2'])
            k.act(g2[:], g2[:], AF.Sigmoid, ['g2'], ['g2'], scale=GELU_C)
            k.tt('pool', ge[:], g2[:], gt[b][:], ALU.mult, ['g2', f'gt{b}'], ['ge'])
            k.tt('dve', ot[b][:], h[:], ge[:], ALU.mult, ['h', 'ge'], [f'ot{b}'])
            k.dma('pool', odT[prow, c * TT:(c + 1) * TT], ot[b][:], r=[f'ot{b}'], final=True)
            yield


def build_LRU(L, k=None):
    k = k or K()
    for _ in gen_LRU(L, k):
        pass
    return k.finish()


def lru_host_inputs(s, xb, gate, prm):
    cs = slice(256 * s, 256 * s + 256)
    col = lambda v: np.ascontiguousarray(v[cs].reshape(2, 128).T)
    Wa = np.zeros((2, 128, 128), np.float32)
    Wx = np.zeros((2, 128, 128), np.float32)
    for pb in range(2):
        for bl in range(2):
            blk = 4 * s + 2 * pb + bl
            Wa[pb, bl * 64:(bl + 1) * 64, bl * 64:(bl + 1) * 64] = prm['lru_w_a'][blk]
            Wx[pb, bl * 64:(bl + 1) * 64, bl * 64:(bl + 1) * 64] = prm['lru_w_x'][blk]
    cw = np.ascontiguousarray(prm['lru_conv_w'][:, cs].reshape(4, 2, 128).transpose(2, 1, 0))
    return dict(xbT=np.ascontiguousarray(xb[:, cs].T), gateT=np.ascontiguousarray(gate[:, cs].T), cw=cw,
                cb=col(prm['lru_conv_b']), Wa=Wa, Wx=Wx, ba=col(prm['lru_b_a']), bx=col(prm['lru_b_x']),
                lam=col(prm['lru_lambda']))


GN_EPS = 64e-5
NLEV = 5


def build_RWKV(L, k=None, NH=4, fr=False, CH=64):
    k = k or K()
    NT = L // 128
    W = NH * 64
    NG = NH // 4
    FR = mybir.dt.float32r if fr else F32
    rd = (lambda ap: ap.bitcast(F32)) if fr else (lambda ap: ap)
    NCK = 128 // CH
    nlev = 5 if CH == 64 else 6
    frc = fr and CH == 128
    FRC = mybir.dt.float32r if frc else F32
    rdc = (lambda ap: ap.bitcast(F32)) if frc else (lambda ap: ap)
    lhc = (lambda ap: ap) if frc else rd
    prkv = [k.din(nm, [L, W]) for nm in ("pr", "pk", "pv")]
    mu1 = k.din("mu1", [3 * W])
    pls = [k.din("plw", [64, L]), k.din("pla", [64, L]), k.din("plg", [128, L])]
    mul = k.din("mul", [128, 3])
    w2 = k.din("w2", [64, W])
    a2 = k.din("a2", [64, W])
    g2 = k.din("g2", [128, W])
    vecs = k.din("vecs", [7, W])
    ident_d = k.din("ident", [128, 128])
    triw_d = k.din("triw", [3, 128, 128])
    mask5_d = k.din("mask5", [128, 640])
    rowm_d = k.din("rowm", [128, 2])
    oc = k.dout("oc", [L, W])

    k.consts(ident_d)
    triw = k.sb("triw_s", [128, 3, 128])
    k.dma('sp', triw[:], triw_d.rearrange("a p n -> p a n"), w=['triw'])
    mask5 = k.sb("mask5_s", [128, 640])
    k.dma('sp', mask5[:], mask5_d, w=['mask5'])
    rowm = k.sb("rowm_s", [128, 2])
    k.dma('sp', rowm[:], rowm_d, w=['rowm'])
    mu1bc = k.bcast_row("mu1bc", mu1, 3 * W)
    vb = [k.bcast_row(f"vb{i}", vecs[i], W) for i in range(7)]
    w0bc, a0bc, kkbc, kabc, rkbc, lngbc, lnbbc = vb
    VK = [f"vb{i}" for i in range(7)]
    muls = k.sb("muls", [128, 3])
    k.dma('sp', muls[:], mul, w=['muls'])
    w2s = k.sb("w2s", [64, W])
    a2s = k.sb("a2s", [64, W])
    k.dma('sp', w2s[:], w2, w=['w2s'])
    k.dma('sp', a2s[:], a2, w=['a2s'])
    g2s = k.sb("g2s", [128, W])
    k.dma('sp', g2s[:], g2, w=['g2s'])
    ST = [k.sb(f"ST{i}", [64, 64], FRC) for i in range(NH)]
    zt = k.sb("zt", [128, W])
    k.memset('dve', zt[:], 0.0, ['zt'])
    for i in range(NH):
        k.cp('dve', ST[i][:], zt[0:64, 0:64], ['zt'], [f'ST{i}'])
    P1s = k.sb("P1s", [128, W], FRC)
    Us = k.sb("Us", [128, W], FRC)
    k.cp('dve', P1s[:], zt[:], ['zt'], ['P1s'])
    k.cp('dve', Us[:], zt[:], ['zt'], ['Us'])

    pt = [k.sb(f"pt{i}", [128, 3 * W]) for i in range(2)]
    pp = [k.sb(f"pp{i}", [128, 3 * W]) for i in range(2)]
    lt = [k.sb(f"lt{i}", [128, 3, 128]) for i in range(2)]
    lp = [k.sb(f"lp{i}", [128, 3, 128]) for i in range(2)]
    for i_ in range(2):
        k.memset('pool', lt[i_][:], 0.0, [f'lt{i_}0', f'lt{i_}1', f'lt{i_}2'])
        k.memset('pool', lp[i_][:], 0.0, [f'lp{i_}0', f'lp{i_}1', f'lp{i_}2', f'lp{i_}z'])
    pm = k.sb("pm", [128, 3 * W])
    vr = k.sb("vr", [128, W], FR)
    lm = k.sb("lm", [128, 3, 128])
    sw = k.sb("sw", [128, W])
    av = k.sb("av", [128, W])
    gv = k.sb("gv", [128, W])
    kkr = k.sb("kkr", [128, W])
    sq = k.sb("sq", [128, W])
    s4 = k.sb("s4", [128, NH])
    rn = k.sb("rn", [128, NH])
    nkk = k.sb("nkk", [128, W])
    kmod = k.sb("kmod", [128, W])
    kka = k.sb("kka", [128, W])
    tmp = k.sb("tmp", [128, W])
    bon = k.sb("bon", [128, NH])
    E1 = k.sb("E1", [128, W])
    E2 = k.sb("E2", [128, W])
    E3 = k.sb("E3", [128, W])
    E4 = k.sb("E4", [128, W])
    E1T = k.sb("E1T", [64, NH, 128])
    At = k.sb("At", [128, W])
    Bs = k.sb("Bs", [128, W])
    Ks = k.sb("Ks", [128, W])
    Rt = k.sb("Rt", [128, W])
    Bfm = [k.sb(f"Bfm{c}", [128, W]) for c in range(2)]
    Kfm = [k.sb(f"Kfm{c}", [128, W]) for c in range(2)]
    FT = [k.sb(f"FT{h}", [64, 4, 128], FR) for h in range(NH)]
    A5 = [k.sb(f"A5_{h}", [128, 640], FR) for h in range(NH)]
    NL = [k.sb(f"NL_{h}", [128, 256], FR) for h in range(NH)]
    PQ = [k.sb(f"PQ_{h}", [128, 256], FR) for h in range(NH)]
    W1 = k.sb("W1", [128, W], FR)
    U1 = k.sb("U1", [128, W])
    ysb = k.sb("ysb", [128, W])
    yc = k.sb("yc", [128, W])
    m4 = k.sb("m4", [128, NH])
    r4 = k.sb("r4", [128, NH])
    ot = [k.sb(f"ot{i}", [128, W]) for i in range(2)]
    B = [k.ps(f"psB{i}", [128, 512]) for i in range(8)]
    bk = lambda i: f'psB{i}'
    v3 = lambda t: t.rearrange("p (h j) -> p h j", h=NH)
    bc4 = lambda t: t.unsqueeze(2).broadcast_to([128, NH, 64])

    for i in range(NT):
        b = i % 2
        rows = slice(i * 128, (i + 1) * 128)
        PK, PPK, LTK, LPK = [], [], [], []
        for q in range(3):
            cq = slice(q * W, (q + 1) * W)
            k.dma('sp', pt[b][:, cq], prkv[q][rows, :], w=[f'pt{b}{q}'])
            PK.append(f'pt{b}{q}')
            if i == 0:
                k.dma('sp', pp[b][1:128, cq], prkv[q][0:127, :], w=[f'pp{b}{q}'])
            else:
                k.dma('sp', pp[b][:, cq], prkv[q][i * 128 - 1:i * 128 + 127, :], w=[f'pp{b}{q}'])
            PPK.append(f'pp{b}{q}')
            nr = pls[q].shape[0]
            k.dma('sp', lt[b][0:nr, q, :], pls[q][:, rows], w=[f'lt{b}{q}'])
            LTK.append(f'lt{b}{q}')
            if i == 0:
                k.dma('sp', lp[b][0:nr, q, 1:128], pls[q][:, 0:127], w=[f'lp{b}{q}'])
            else:
                k.dma('sp', lp[b][0:nr, q, :], pls[q][:, i * 128 - 1:i * 128 + 127], w=[f'lp{b}{q}'])
            LPK.append(f'lp{b}{q}')
        if i == 0:
            k.memset('pool', pp[b][0:1, :], 0.0, [f'pp{b}z'])
            k.memset('pool', lp[b][:, :, 0:1], 0.0, [f'lp{b}z'])
            PPK.append(f'pp{b}z')
            LPK.append(f'lp{b}z')
        k.tt('pool', pm[:], pp[b][:], pt[b][:], ALU.subtract, PPK + PK, ['pm'])
        k.tt('pool', pm[:], pm[:], mu1bc[:], ALU.mult, ['pm', 'mu1bc'], ['pm'])
        k.tt('pool', pm[:], pm[:], pt[b][:], ALU.add, ['pm'] + PK, ['pm'])
        r_, k_, v_ = pm[:, 0:W], pm[:, W:2 * W], pm[:, 2 * W:3 * W]
        k.cp('act', vr[:], v_, ['pm'], ['vr'])
        LK = LTK + LPK
        k.tt('dve', lm[:], lp[b][:], lt[b][:], ALU.subtract, LK, ['lm'])
        for blk in range(3):
            k.stt(lm[:, blk, :], lm[:, blk, :], muls[:, blk:blk + 1], lt[b][:, blk, :], ALU.mult, ALU.add,
                  ['lm', 'muls'] + LK, ['lm'])
        k.act(lm[0:64, 0, :], lm[0:64, 0, :], AF.Tanh, ['lm'], ['lm'])
        k.act(lm[:, 2, :], lm[:, 2, :], AF.Sigmoid, ['lm'], ['lm'])
        k.mm(B[0][:, 0:W], lm[0:64, 0, :], w2s[:], True, True, ['lm', 'w2s'], [bk(0)])
        k.mm(B[1][:, 0:W], lm[0:64, 1, :], a2s[:], True, True, ['lm', 'a2s'], [bk(1)])
        k.mm(B[2][:, 0:W], lm[:, 2, :], g2s[:], True, True, ['lm', 'g2s'], [bk(2)])
        k.tt('dve', sw[:], B[0][:, 0:W], w0bc[:], ALU.add, [bk(0), VK[0]], ['sw'])
        k.act(sw[:], sw[:], AF.Sigmoid, ['sw'], ['sw'])
        k.tt('dve', av[:], B[1][:, 0:W], a0bc[:], ALU.add, [bk(1), VK[1]], ['av'])
        k.act(av[:], av[:], AF.Sigmoid, ['av'], ['av'])
        k.cp('act', gv[:], B[2][:, 0:W], [bk(2)], ['gv'])
        k.tt('pool', kkr[:], k_, kkbc[:], ALU.mult, ['pm', VK[2]], ['kkr'])
        k.tt('pool', sq[:], kkr[:], kkr[:], ALU.mult, ['kkr'], ['sq'])
        k.P.op('dve', lambda e: e.tensor_reduce(out=s4[:], in_=v3(sq[:]), axis=AX.X, op=ALU.add), reads=['sq'], writes=['s4'])
        k.act(s4[:], s4[:], AF.Sqrt, ['s4'], ['s4'])
        k.ts('dve', s4[:], s4[:], 1e-12, None, ALU.max, None, ['s4'], ['s4'])
        k.recip(rn[:], s4[:], ['s4'], ['rn'])
        k.ts('dve', rn[:], rn[:], -1.0, None, ALU.mult, None, ['rn'], ['rn'])
        k.tt('dve', v3(nkk[:]), v3(kkr[:]), bc4(rn[:]), ALU.mult, ['kkr', 'rn'], ['nkk'])
        k.stt(tmp[:], av[:], -1.0, kabc[:], ALU.add, ALU.mult, ['av', VK[3]], ['tmp'])
        k.stt(kmod[:], tmp[:], 1.0, k_, ALU.add, ALU.mult, ['tmp', 'pm'], ['kmod'])
        k.stt(kka[:], nkk[:], -1.0, av[:], ALU.mult, ALU.mult, ['nkk', 'av'], ['kka'])
        k.tt('pool', tmp[:], r_, kmod[:], ALU.mult, ['pm', 'kmod', 'tmp'], ['tmp'])
        k.tt('pool', tmp[:], tmp[:], rkbc[:], ALU.mult, ['tmp', VK[4]], ['tmp'])
        k.P.op('dve', lambda e: e.tensor_reduce(out=bon[:], in_=v3(tmp[:]), axis=AX.X, op=ALU.add), reads=['tmp'], writes=['bon'])
        k.mm(B[3][:, 0:W], triw[:, 0, :], sw[:], True, True, ['triw', 'sw'], [bk(3)])
        k.mm(B[4][:, 0:W], triw[:, 1, :], sw[:], True, True, ['triw', 'sw'], [bk(4)])
        k.mm(B[5][:, 0:W], triw[:, 2, :], sw[:], True, True, ['triw', 'sw'], [bk(5)])
        for h in range(NH):
            k.mm(B[6 + h // 4][0:64, (h % 4) * 128:(h % 4 + 1) * 128], sw[:, h * 64:(h + 1) * 64], triw[:, 0, :], True, True,
                 ['sw', 'triw'], [bk(6 + h // 4)])
        k.act(E1[:], B[3][:, 0:W], AF.Exp, [bk(3)], ['E1'])
        k.act(E2[:], B[3][:, 0:W], AF.Exp, [bk(3)], ['E2'], scale=-1.0)
        k.act(E3[:], B[4][:, 0:W], AF.Exp, [bk(4)], ['E3'])
        k.act(E4[:], B[5][:, 0:W], AF.Exp, [bk(5)], ['E4'])
        for g in range(NG):
            k.act(E1T[:, 4 * g:4 * g + 4, :].rearrange("p a t -> p (a t)"), B[6 + g][0:64, :], AF.Exp, [bk(6 + g)], ['E1T'])
        k.tt('dve', At[:], nkk[:], E3[:], ALU.mult, ['nkk', 'E3'], ['At'])
        k.tt('pool', Bs[:], kka[:], E2[:], ALU.mult, ['kka', 'E2'], ['Bs'])
        k.tt('dve', Ks[:], kmod[:], E2[:], ALU.mult, ['kmod', 'E2'], ['Ks'])
        k.tt('pool', Rt[:], r_, E1[:], ALU.mult, ['pm', 'E1'], ['Rt'])
        for c in range(NCK):
            k.stt(Bfm[c][:], kka[:], rowm[:, c:c + 1], E4[:], ALU.mult, ALU.mult, ['kka', 'E4', 'rowm'], [f'Bfm{c}'])
            k.stt(Kfm[c][:], kmod[:], rowm[:, c:c + 1], E4[:], ALU.mult, ALU.mult, ['kmod', 'E4', 'rowm'], [f'Kfm{c}'])
        HS = list(range(NH))
        for h in HS:
            cs_ = slice(h * 64, (h + 1) * 64)
            for q, (src, key) in enumerate([(At, 'At'), (Bs, 'Bs'), (Ks, 'Ks'), (Rt, 'Rt')]):
                k.tr(B[h][0:64, q * 128:(q + 1) * 128], src[:, cs_], k.identf[:], [key], [bk(h)])
        for h in HS:
            k.cp('act' if h % 2 else 'dve', FT[h][:].rearrange("p a t -> p (a t)"), B[h][0:64, :], [bk(h)], [f'FT{h}'])
        for h in HS:
            AtT, BsT, KsT, RtT = (FT[h][:, q, :] for q in range(4))
            o = lambda j: B[h][:, j * 128:(j + 1) * 128]
            k.mm(o(0), BsT, AtT, True, True, [f'FT{h}'], [bk(h)])
            k.mm(o(1), AtT, BsT, True, True, [f'FT{h}'], [bk(h)])
            k.mm(o(2), KsT, AtT, True, True, [f'FT{h}'], [bk(h)])
        for h in HS:
            k.tt('dve', A5[h][:, 0:384], B[h][:, 0:384], mask5[:, 0:384], ALU.mult, [bk(h), 'mask5'], [f'A5_{h}'])
        for h in HS:
            AtT, BsT, KsT, RtT = (FT[h][:, q, :] for q in range(4))
            k.mm(B[h][:, 0:128], BsT, RtT, True, True, [f'FT{h}'], [bk(h)])
            k.mm(B[h][:, 128:256], KsT, RtT, True, True, [f'FT{h}'], [bk(h)])
        for h in HS:
            k.tt('dve', A5[h][:, 384:640], B[h][:, 0:256], mask5[:, 384:640], ALU.mult, [bk(h), 'mask5'], [f'A5b_{h}'])
            k.cp('act', NL[h][:], rd(A5[h][:, 0:256]), [f'A5_{h}'], [f'NL_{h}'])
            k.tt('pool' if not fr else 'dve', PQ[h][:].rearrange("p (a n) -> p a n", a=2), rd(A5[h][:, 0:256]).rearrange("p (a n) -> p a n", a=2),
                 k.identf[:].unsqueeze(1).broadcast_to([128, 2, 128]), ALU.add, [f'A5_{h}', 'ident'], [f'PQ_{h}'])
        for lev in range(nlev):
            for h in HS:
                N_, L_ = NL[h][:, 0:128], NL[h][:, 128:256]
                k.mm(B[h][:, 0:128], L_, N_, True, True, [f'NL_{h}'], [bk(h)])
                k.mm(B[h][:, 128:256], N_, L_, True, True, [f'NL_{h}'], [bk(h)])
            for h in HS:
                k.cp('act', NL[h][:], B[h][:, 0:256], [bk(h)], [f'NL_{h}'])
            for h in HS:
                N_, L_ = NL[h][:, 0:128], NL[h][:, 128:256]
                P_, Q_ = PQ[h][:, 0:128], PQ[h][:, 128:256]
                k.mm(B[h][:, 256:384], Q_, N_, True, True, [f'NL_{h}', f'PQ_{h}'], [bk(h)])
                k.mm(B[h][:, 384:512], P_, L_, True, True, [f'NL_{h}', f'PQ_{h}'], [bk(h)])
            for h in HS:
                k.tt('dve', PQ[h][:], B[h][:, 256:512], rd(PQ[h][:]), ALU.add, [bk(h), f'PQ_{h}'], [f'PQ_{h}'])
        for h in range(NH):
            k.mm(B[0][:, h * 64:(h + 1) * 64], A5[h][:, 256:384], vr[:, h * 64:(h + 1) * 64], True, True, [f'A5_{h}', 'vr'], [bk(0)])
        k.cp('act', W1[:], B[0][:, 0:W], [bk(0)], ['W1'])
        for h in range(NH):
            k.mm(B[1][:, h * 64:(h + 1) * 64], PQ[h][:, 0:128], W1[:, h * 64:(h + 1) * 64], True, True,
                 [f'PQ_{h}', 'W1'], [bk(1)])
        k.cp('act', U1[:], B[1][:, 0:W], [bk(1)], ['U1'])
        vsrc = vr if frc else None
        for c in range(NCK):
            cr = slice(c * CH, (c + 1) * CH)
            for h in range(NH):
                k.mm(B[2][cr, h * 64:(h + 1) * 64], lhc(FT[h][:, 0, cr]), ST[h][:], True, True, [f'FT{h}', f'ST{h}'], [bk(2)])
            k.cp('act', P1s[cr, :], B[2][cr, 0:W], [bk(2)], ['P1s'])
            for h in range(NH):
                k.mm(B[3][cr, h * 64:(h + 1) * 64], lhc(PQ[h][:, cr]), P1s[:, h * 64:(h + 1) * 64], True, True,
                     [f'PQ_{h}', 'P1s'], [bk(3)])
            k.tt('dve', Us[cr, :], B[3][cr, 0:W], U1[cr, :], ALU.add, [bk(3), 'U1'], ['Us'])
            for h in range(NH):
                hc_ = slice(h * 64, (h + 1) * 64)
                vh = vr[:, hc_] if frc else pm[:, 2 * W + h * 64:2 * W + (h + 1) * 64]
                vk = 'vr' if frc else 'pm'
                k.mm(B[6][cr, hc_], lhc(FT[h][:, 3, cr]), ST[h][:], True, False, [f'FT{h}', f'ST{h}'], [bk(6)])
                k.mm(B[6][cr, hc_], lhc(A5[h][:, 384:512][:, cr]), Us[:, hc_], False, False, [f'A5b_{h}', 'Us'], [bk(6)])
                k.mm(B[6][cr, hc_], lhc(A5[h][:, 512:640][:, cr]), vh, False, True, [f'A5b_{h}', vk], [bk(6)])
            for h in range(NH):
                hc_ = slice(h * 64, (h + 1) * 64)
                vh = pm[:, 2 * W + h * 64:2 * W + (h + 1) * 64]
                k.mm(B[7][0:64, hc_], Bfm[c][:, hc_], rdc(Us[:, hc_]), True, False, [f'Bfm{c}', 'Us'], [bk(7)])
                k.mm(B[7][0:64, hc_], Kfm[c][:, hc_], vh, False, True, [f'Kfm{c}', 'pm'], [bk(7)])
            for h in range(NH):
                hc_ = slice(h * 64, (h + 1) * 64)
                k.stt(ST[h][:], rdc(ST[h][:]), E1T[:, h, (c + 1) * CH - 1:(c + 1) * CH], B[7][0:64, hc_], ALU.mult, ALU.add,
                      [f'ST{h}', 'E1T', bk(7)], [f'ST{h}'])
        k.cp('act', ysb[:], B[6][:, 0:W], [bk(6)], ['ysb'])
        k.P.op('dve', lambda e: e.tensor_reduce(out=m4[:], in_=v3(ysb[:]), axis=AX.X, op=ALU.add), reads=['ysb'], writes=['m4'])
        k.ts('dve', m4[:], m4[:], -1.0 / 64.0, None, ALU.mult, None, ['m4'], ['m4'])
        k.tt('dve', v3(yc[:]), v3(ysb[:]), bc4(m4[:]), ALU.add, ['ysb', 'm4'], ['yc'])
        k.tt('pool', sq[:], yc[:], yc[:], ALU.mult, ['yc'], ['sq'])
        k.P.op('dve', lambda e: e.tensor_reduce(out=r4[:], in_=v3(sq[:]), axis=AX.X, op=ALU.add), reads=['sq'], writes=['r4'])
        k.ts('dve', r4[:], r4[:], 1.0 / 64.0, GN_EPS, ALU.mult, ALU.add, ['r4'], ['r4'])
        k.act(r4[:], r4[:], AF.Sqrt, ['r4'], ['r4'])
        k.recip(r4[:], r4[:], ['r4'], ['r4'])
        k.tt('dve', v3(yc[:]), v3(yc[:]), bc4(r4[:]), ALU.mult, ['yc', 'r4'], ['yc'])
        k.tt('pool', yc[:], yc[:], lngbc[:], ALU.mult, ['yc', VK[5]], ['yc'])
        k.tt('pool', yc[:], yc[:], lnbbc[:], ALU.add, ['yc', VK[6]], ['yc'])
        k.tt('dve', v3(tmp[:]), v3(v_), bc4(bon[:]), ALU.mult, ['pm', 'bon', 'tmp'], ['tmp'])
        k.tt('pool', yc[:], yc[:], tmp[:], ALU.add, ['yc', 'tmp'], ['yc'])
        k.tt('dve', ot[b][:], yc[:], gv[:], ALU.mult, ['yc', 'gv'], [f'ot{b}'])
        k.dma('pool', oc[rows, :], ot[b][:], r=[f'ot{b}'], final=True)
    return k.finish()


def build_RWKVP(L, k=None, CH=64):
    NH, fr = 8, True
    k = k or K()
    NT = L // 128
    W = NH * 64
    NG = NH // 4
    FR = mybir.dt.float32r if fr else F32
    rd = (lambda ap: ap.bitcast(F32)) if fr else (lambda ap: ap)
    NCK = 128 // CH
    nlev = 5 if CH == 64 else 6
    frc = fr and CH == 128
    FRC = mybir.dt.float32r if frc else F32
    rdc = (lambda ap: ap.bitcast(F32)) if frc else (lambda ap: ap)
    lhc = (lambda ap: ap) if frc else rd
    prkv = [k.din(nm, [L, W]) for nm in ("pr", "pk", "pv")]
    mu1 = k.din("mu1", [3 * W])
    pls = [k.din("plw", [64, L]), k.din("pla", [64, L]), k.din("plg", [128, L])]
    mul = k.din("mul", [128, 3])
    w2 = k.din("w2", [64, W])
    a2 = k.din("a2", [64, W])
    g2 = k.din("g2", [128, W])
    vecs = k.din("vecs", [7, W])
    ident_d = k.din("ident", [128, 128])
    triw_d = k.din("triw", [3, 128, 128])
    mask5_d = k.din("mask5", [128, 640])
    rowm_d = k.din("rowm", [128, 2])
    oc = k.dout("oc", [L, W])

    k.consts(ident_d)
    triw = k.sb("triw_s", [128, 3, 128])
    k.dma('sp', triw[:], triw_d.rearrange("a p n -> p a n"), w=['triw'])
    mask5 = k.sb("mask5_s", [128, 640])
    k.dma('sp', mask5[:], mask5_d, w=['mask5'])
    rowm = k.sb("rowm_s", [128, 2])
    k.dma('sp', rowm[:], rowm_d, w=['rowm'])
    mu1bc = k.bcast_row("mu1bc", mu1, 3 * W)
    vb = [k.bcast_row(f"vb{i}", vecs[i], W) for i in range(7)]
    w0bc, a0bc, kkbc, kabc, rkbc, lngbc, lnbbc = vb
    VK = [f"vb{i}" for i in range(7)]
    muls = k.sb("muls", [128, 3])
    k.dma('sp', muls[:], mul, w=['muls'])
    w2s = k.sb("w2s", [64, W])
    a2s = k.sb("a2s", [64, W])
    k.dma('sp', w2s[:], w2, w=['w2s'])
    k.dma('sp', a2s[:], a2, w=['a2s'])
    g2s = k.sb("g2s", [128, W])
    k.dma('sp', g2s[:], g2, w=['g2s'])
    ST = [k.sb(f"ST{i}", [64, 64], FRC) for i in range(NH)]
    zt = k.sb("zt", [128, W])
    k.memset('dve', zt[:], 0.0, ['zt'])
    for i in range(NH):
        k.cp('dve', ST[i][:], zt[0:64, 0:64], ['zt'], [f'ST{i}'])
    P1s = k.sb("P1s", [128, W], FRC)
    Us = k.sb("Us", [128, W], FRC)
    k.cp('dve', P1s[:], zt[:], ['zt'], ['P1s'])
    k.cp('dve', Us[:], zt[:], ['zt'], ['Us'])

    pt = [k.sb("pt0", [128, 3 * W])] * 2
    pp = [k.sb("pp0", [128, 3 * W])] * 2
    lt = [k.sb("lt0", [128, 3, 128])] * 2
    lp = [k.sb("lp0", [128, 3, 128])] * 2
    k.memset('pool', lt[0][:], 0.0, ['lt0', 'lt1', 'lt2'])
    k.memset('pool', lp[0][:], 0.0, ['lp0', 'lp1', 'lp2', 'lpz'])
    pm2 = [k.sb(f"pm{i_}", [128, 3 * W]) for i_ in range(2)]
    vr2 = [k.sb(f"vr{i_}", [128, W], FR) for i_ in range(2)]
    lm2 = [k.sb(f"lm{i_}", [128, 3, 128]) for i_ in range(2)]
    sw = k.sb("sw", [128, W])
    av = k.sb("av", [128, W])
    gv2 = [k.sb(f"gv{i_}", [128, W]) for i_ in range(2)]
    kkr = k.sb("kkr", [128, W])
    sq = k.sb("sq", [128, W])
    s4 = k.sb("s4", [128, NH])
    rn = k.sb("rn", [128, NH])
    nkk = k.sb("nkk", [128, W])
    kmod = k.sb("kmod", [128, W])
    kka = k.sb("kka", [128, W])
    tmp = k.sb("tmp", [128, W])
    bon2 = [k.sb(f"bon{i_}", [128, NH]) for i_ in range(2)]
    E1 = k.sb("E1", [128, W])
    E2 = k.sb("E2", [128, W])
    E3 = k.sb("E3", [128, W])
    E4 = k.sb("E4", [128, W])
    E1T2 = [k.sb(f"E1T{i_}", [64, NH, 128]) for i_ in range(2)]
    At2 = [k.sb(f"At{i_}", [128, W]) for i_ in range(2)]
    Bs2 = [k.sb(f"Bs{i_}", [128, W]) for i_ in range(2)]
    Ks2 = [k.sb(f"Ks{i_}", [128, W]) for i_ in range(2)]
    Rt2 = [k.sb(f"Rt{i_}", [128, W]) for i_ in range(2)]
    Bfm2 = [[k.sb(f"Bfm{p_}{c}", [128, W]) for c in range(NCK)] for p_ in range(2)]
    Kfm2 = [[k.sb(f"Kfm{p_}{c}", [128, W]) for c in range(NCK)] for p_ in range(2)]
    sqp = k.sb("sqp", [128, W])
    tmpp = k.sb("tmpp", [128, W])
    FT = [k.sb(f"FT{h}", [64, 4, 128], FR) for h in range(NH)]
    A5 = [k.sb(f"A5_{h}", [128, 640], FR) for h in range(NH)]
    NL = [k.sb(f"NL_{h}", [128, 256], FR) for h in range(NH)]
    PQ = [k.sb(f"PQ_{h}", [128, 128], FR) for h in range(NH)]
    W1 = k.sb("W1", [128, W], FR)
    U1 = k.sb("U1", [128, W])
    ysb = k.sb("ysb", [128, W])
    yc = k.sb("yc", [128, W])
    m4 = k.sb("m4", [128, NH])
    r4 = k.sb("r4", [128, NH])
    ot = [k.sb(f"ot{i}", [128, W]) for i in range(2)]
    B = [k.ps(f"psB{i}", [128, 512]) for i in range(8)]
    bk = lambda i: f'psB{i}'
    v3 = lambda t: t.rearrange("p (h j) -> p h j", h=NH)
    bc4 = lambda t: t.unsqueeze(2).broadcast_to([128, NH, 64])


    S0, S1, C0, C1 = 6, 7, 4, 5

    def tile(i):
        b = i % 2
        pm, lm = pm2[b], lm2[b]
        kpm, klm = f'pm{b}', f'lm{b}'
        At, Bs, Ks, Rt, gv, vr, bon, E1T, Bf, Kf = At2[b], Bs2[b], Ks2[b], Rt2[b], gv2[b], vr2[b], bon2[b], E1T2[b], Bfm2[b], Kfm2[b]
        kAt, kBs, kKs, kRt, kgv, kvr, kbon, kE1T, kBf, kKf = (f'{n_}{b}' for n_ in ('At', 'Bs', 'Ks', 'Rt', 'gv', 'vr', 'bon', 'E1T', 'Bf', 'Kf'))
        rows = slice(i * 128, (i + 1) * 128)
        PK, PPK, LTK, LPK = [], [], [], []
        for q in range(3):
            cq = slice(q * W, (q + 1) * W)
            k.dma('sp', pt[b][:, cq], prkv[q][rows, :], w=[f'pt{q}'])
            PK.append(f'pt{q}')
            if i == 0:
                k.dma('sp', pp[b][1:128, cq], prkv[q][0:127, :], w=[f'pp{q}'])
            else:
                k.dma('sp', pp[b][:, cq], prkv[q][i * 128 - 1:i * 128 + 127, :], w=[f'pp{q}'])
            PPK.append(f'pp{q}')
            nr = pls[q].shape[0]
            k.dma('sp', lt[b][0:nr, q, :], pls[q][:, rows], w=[f'lt{q}'])
            LTK.append(f'lt{q}')
            if i == 0:
                k.dma('sp', lp[b][0:nr, q, 1:128], pls[q][:, 0:127], w=[f'lp{q}'])
            else:
                k.dma('sp', lp[b][0:nr, q, :], pls[q][:, i * 128 - 1:i * 128 + 127], w=[f'lp{q}'])
            LPK.append(f'lp{q}')
        if i == 0:
            k.memset('pool', pp[b][0:1, :], 0.0, ['ppz'])
            k.memset('pool', lp[b][:, :, 0:1], 0.0, ['lpz'])
            PPK.append('ppz')
            LPK.append('lpz')
        k.tt('pool', pm[:], pp[b][:], pt[b][:], ALU.subtract, PPK + PK, [kpm])
        k.tt('pool', pm[:], pm[:], mu1bc[:], ALU.mult, [kpm, 'mu1bc'], [kpm])
        k.tt('pool', pm[:], pm[:], pt[b][:], ALU.add, [kpm] + PK, [kpm])
        r_, k_, v_ = pm[:, 0:W], pm[:, W:2 * W], pm[:, 2 * W:3 * W]
        LK = LTK + LPK
        k.tt('dve', lm[:], lp[b][:], lt[b][:], ALU.subtract, LK, [klm])
        for blk in range(3):
            k.stt(lm[:, blk, :], lm[:, blk, :], muls[:, blk:blk + 1], lt[b][:, blk, :], ALU.mult, ALU.add,
                  [klm, 'muls'] + LK, [klm])
        k.act(lm[0:64, 0, :], lm[0:64, 0, :], AF.Tanh, [klm], [klm])
        k.act(lm[:, 2, :], lm[:, 2, :], AF.Sigmoid, [klm], [klm])
        yield
        k.cp('act', vr[:], v_, [kpm], [kvr])
        k.mm(B[S0][:, 0:W], lm[0:64, 0, :], w2s[:], True, True, [klm, 'w2s'], [bk(S0)])
        k.mm(B[S1][:, 0:W], lm[0:64, 1, :], a2s[:], True, True, [klm, 'a2s'], [bk(S1)])
        k.tt('dve', sw[:], B[S0][:, 0:W], w0bc[:], ALU.add, [bk(S0), VK[0]], ['sw'])
        k.act(sw[:], sw[:], AF.Sigmoid, ['sw'], ['sw'])
        k.tt('dve', av[:], B[S1][:, 0:W], a0bc[:], ALU.add, [bk(S1), VK[1]], ['av'])
        k.act(av[:], av[:], AF.Sigmoid, ['av'], ['av'])
        k.mm(B[S0][:, 0:W], lm[:, 2, :], g2s[:], True, True, [klm, 'g2s'], [bk(S0)])
        k.cp('act', gv[:], B[S0][:, 0:W], [bk(S0)], [kgv])
        yield
        k.tt('pool', kkr[:], k_, kkbc[:], ALU.mult, [kpm, VK[2]], ['kkr'])
        k.tt('pool', sq[:], kkr[:], kkr[:], ALU.mult, ['kkr'], ['sq'])
        k.P.op('dve', lambda e: e.tensor_reduce(out=s4[:], in_=v3(sq[:]), axis=AX.X, op=ALU.add), reads=['sq'], writes=['s4'])
        k.act(s4[:], s4[:], AF.Sqrt, ['s4'], ['s4'])
        k.ts('dve', s4[:], s4[:], 1e-12, None, ALU.max, None, ['s4'], ['s4'])
        k.recip(rn[:], s4[:], ['s4'], ['rn'])
        k.ts('dve', rn[:], rn[:], -1.0, None, ALU.mult, None, ['rn'], ['rn'])
        k.tt('dve', v3(nkk[:]), v3(kkr[:]), bc4(rn[:]), ALU.mult, ['kkr', 'rn'], ['nkk'])
        k.stt(tmp[:], av[:], -1.0, kabc[:], ALU.add, ALU.mult, ['av', VK[3]], ['tmp'])
        k.stt(kmod[:], tmp[:], 1.0, k_, ALU.add, ALU.mult, ['tmp', kpm], ['kmod'])
        k.stt(kka[:], nkk[:], -1.0, av[:], ALU.mult, ALU.mult, ['nkk', 'av'], ['kka'])
        k.tt('pool', tmp[:], r_, kmod[:], ALU.mult, [kpm, 'kmod', 'tmp'], ['tmp'])
        k.tt('pool', tmp[:], tmp[:], rkbc[:], ALU.mult, ['tmp', VK[4]], ['tmp'])
        k.P.op('dve', lambda e: e.tensor_reduce(out=bon[:], in_=v3(tmp[:]), axis=AX.X, op=ALU.add), reads=['tmp'], writes=[kbon])
        k.mm(B[S1][:, 0:W], triw[:, 0, :], sw[:], True, True, ['triw', 'sw'], [bk(S1)])
        k.mm(B[S0][:, 0:W], triw[:, 1, :], sw[:], True, True, ['triw', 'sw'], [bk(S0)])
        k.act(E1[:], B[S1][:, 0:W], AF.Exp, [bk(S1)], ['E1'])
        k.act(E2[:], B[S1][:, 0:W], AF.Exp, [bk(S1)], ['E2'], scale=-1.0)
        k.act(E3[:], B[S0][:, 0:W], AF.Exp, [bk(S0)], ['E3'])
        k.mm(B[S1][:, 0:W], triw[:, 2, :], sw[:], True, True, ['triw', 'sw'], [bk(S1)])
        k.act(E4[:], B[S1][:, 0:W], AF.Exp, [bk(S1)], ['E4'])
        for g in range(2):
            for hl in range(4):
                h = 4 * g + hl
                k.mm(B[S0 + g][0:64, hl * 128:(hl + 1) * 128], sw[:, h * 64:(h + 1) * 64], triw[:, 0, :], True, True,
                     ['sw', 'triw'], [bk(S0 + g)])
        for g in range(2):
            k.act(E1T[:, 4 * g:4 * g + 4, :].rearrange("p a t -> p (a t)"), B[S0 + g][0:64, :], AF.Exp, [bk(S0 + g)], [kE1T])
        yield
        k.tt('dve', At[:], nkk[:], E3[:], ALU.mult, ['nkk', 'E3'], [kAt])
        k.tt('pool', Bs[:], kka[:], E2[:], ALU.mult, ['kka', 'E2'], [kBs])
        k.tt('dve', Ks[:], kmod[:], E2[:], ALU.mult, ['kmod', 'E2'], [kKs])
        k.tt('pool', Rt[:], r_, E1[:], ALU.mult, [kpm, 'E1'], [kRt])
        for c in range(NCK):
            k.stt(Bf[c][:], kka[:], rowm[:, c:c + 1], E4[:], ALU.mult, ALU.mult, ['kka', 'E4', 'rowm'], [kBf])
            k.stt(Kf[c][:], kmod[:], rowm[:, c:c + 1], E4[:], ALU.mult, ALU.mult, ['kmod', 'E4', 'rowm'], [kKf])
        yield
        for g in range(2):
            HS = list(range(4 * g, 4 * g + 4))
            for h in HS:
                hl = h % 4
                cs_ = slice(h * 64, (h + 1) * 64)
                for q, (src, key) in enumerate([(At, kAt), (Bs, kBs), (Ks, kKs), (Rt, kRt)]):
                    k.tr(B[hl][0:64, q * 128:(q + 1) * 128], src[:, cs_], k.identf[:], [key], [bk(hl)])
            for h in HS:
                hl = h % 4
                k.cp('act' if h % 2 else 'dve', FT[h][:].rearrange("p a t -> p (a t)"), B[hl][0:64, :], [bk(hl)], [f'FT{h}'])
            for h in HS:
                hl = h % 4
                AtT, BsT, KsT, RtT = (FT[h][:, q, :] for q in range(4))
                k.mm(B[hl][:, 0:128], BsT, AtT, True, True, [f'FT{h}'], [bk(hl)])
                k.mm(B[hl][:, 128:256], AtT, BsT, True, True, [f'FT{h}'], [bk(hl)])
                k.mm(B[hl][:, 256:384], KsT, AtT, True, True, [f'FT{h}'], [bk(hl)])
            for h in HS:
                hl = h % 4
                k.tt('dve', A5[h][:, 0:384], B[hl][:, 0:384], mask5[:, 0:384], ALU.mult, [bk(hl), 'mask5'], [f'A5_{h}'])
            for h in HS:
                hl = h % 4
                AtT, BsT, KsT, RtT = (FT[h][:, q, :] for q in range(4))
                k.mm(B[hl][:, 0:128], BsT, RtT, True, True, [f'FT{h}'], [bk(hl)])
                k.mm(B[hl][:, 128:256], KsT, RtT, True, True, [f'FT{h}'], [bk(hl)])
            for h in HS:
                hl = h % 4
                k.tt('dve', A5[h][:, 384:640], B[hl][:, 0:256], mask5[:, 384:640], ALU.mult, [bk(hl), 'mask5'], [f'A5b_{h}'])
                k.cp('act', NL[h][:], rd(A5[h][:, 0:256]), [f'A5_{h}'], [f'NL_{h}'])
                k.tt('dve', PQ[h][:, 0:128], rd(A5[h][:, 0:128]), k.identf[:], ALU.add, [f'A5_{h}', 'ident'], [f'PQ_{h}'])
            for lev in range(nlev):
                last = (lev == nlev - 1)
                for h in HS:
                    hl = h % 4
                    N_, L_ = NL[h][:, 0:128], NL[h][:, 128:256]
                    k.mm(B[hl][:, 0:128], L_, N_, True, True, [f'NL_{h}'], [bk(hl)])
                    k.mm(B[hl][:, 128:256], N_, L_, True, True, [f'NL_{h}'], [bk(hl)])
                for h in HS:
                    hl = h % 4
                    k.cp('act', NL[h][:], B[hl][:, 0:256], [bk(hl)], [f'NL_{h}'])
                for h in HS:
                    hl = h % 4
                    k.mm(B[hl][:, 256:384], NL[h][:, 128:256], PQ[h][:, 0:128], True, True, [f'NL_{h}', f'PQ_{h}'], [bk(hl)])
                for h in HS:
                    hl = h % 4
                    k.tt('dve', PQ[h][:, 0:128], B[hl][:, 256:384], rd(PQ[h][:, 0:128]), ALU.add, [bk(hl), f'PQ_{h}'], [f'PQ_{h}'])
            yield
        for h in range(NH):
            k.mm(B[C0][:, h * 64:(h + 1) * 64], A5[h][:, 256:384], vr[:, h * 64:(h + 1) * 64], True, True, [f'A5_{h}', kvr], [bk(C0)])
        k.cp('act', W1[:], B[C0][:, 0:W], [bk(C0)], ['W1'])
        for h in range(NH):
            k.mm(B[C1][:, h * 64:(h + 1) * 64], PQ[h][:, 0:128], W1[:, h * 64:(h + 1) * 64], True, True,
                 [f'PQ_{h}', 'W1'], [bk(C1)])
        k.cp('act', U1[:], B[C1][:, 0:W], [bk(C1)], ['U1'])
        for c in range(NCK):
            cr = slice(c * CH, (c + 1) * CH)
            for h in range(NH):
                k.mm(B[C0][cr, h * 64:(h + 1) * 64], lhc(FT[h][:, 0, cr]), ST[h][:], True, True, [f'FT{h}', f'ST{h}'], [bk(C0)])
            k.cp('act', P1s[cr, :], B[C0][cr, 0:W], [bk(C0)], ['P1s'])
            for h in range(NH):
                k.mm(B[C0][cr, h * 64:(h + 1) * 64], lhc(PQ[h][:, cr]), P1s[:, h * 64:(h + 1) * 64], True, True,
                     [f'PQ_{h}', 'P1s'], [bk(C0)])
            k.tt('dve', Us[cr, :], B[C0][cr, 0:W], U1[cr, :], ALU.add, [bk(C0), 'U1'], ['Us'])
            for h in range(NH):
                hc_ = slice(h * 64, (h + 1) * 64)
                vh = vr[:, hc_] if frc else rd(vr[:, hc_])
                k.mm(B[C0][cr, hc_], lhc(FT[h][:, 3, cr]), ST[h][:], True, False, [f'FT{h}', f'ST{h}'], [bk(C0)])
                k.mm(B[C0][cr, hc_], lhc(A5[h][:, 384:512][:, cr]), Us[:, hc_], False, False, [f'A5b_{h}', 'Us'], [bk(C0)])
                k.mm(B[C0][cr, hc_], lhc(A5[h][:, 512:640][:, cr]), vh, False, True, [f'A5b_{h}', kvr], [bk(C0)])
            for h in range(NH):
                hc_ = slice(h * 64, (h + 1) * 64)
                k.mm(B[C1][0:64, hc_], Bf[c][:, hc_], rdc(Us[:, hc_]), True, False, [kBf, 'Us'], [bk(C1)])
                k.mm(B[C1][0:64, hc_], Kf[c][:, hc_], rd(vr[:, hc_]), False, True, [kKf, kvr], [bk(C1)])
            for h in range(NH):
                hc_ = slice(h * 64, (h + 1) * 64)
                k.stt(ST[h][:], rdc(ST[h][:]), E1T[:, h, (c + 1) * CH - 1:(c + 1) * CH], B[C1][0:64, hc_], ALU.mult, ALU.add,
                      [f'ST{h}', kE1T, bk(C1)], [f'ST{h}'])
        k.cp('act', ysb[:], B[C0][:, 0:W], [bk(C0)], ['ysb'])
        k.P.op('dve', lambda e: e.tensor_reduce(out=m4[:], in_=v3(ysb[:]), axis=AX.X, op=ALU.add), reads=['ysb'], writes=['m4'])
        k.ts('dve', m4[:], m4[:], -1.0 / 64.0, None, ALU.mult, None, ['m4'], ['m4'])
        k.tt('dve', v3(yc[:]), v3(ysb[:]), bc4(m4[:]), ALU.add, ['ysb', 'm4'], ['yc'])
        k.tt('pool', sqp[:], yc[:], yc[:], ALU.mult, ['yc'], ['sqp'])
        k.P.op('dve', lambda e: e.tensor_reduce(out=r4[:], in_=v3(sqp[:]), axis=AX.X, op=ALU.add), reads=['sqp'], writes=['r4'])
        k.ts('dve', r4[:], r4[:], 1.0 / 64.0, GN_EPS, ALU.mult, ALU.add, ['r4'], ['r4'])
        k.act(r4[:], r4[:], AF.Sqrt, ['r4'], ['r4'])
        k.recip(r4[:], r4[:], ['r4'], ['r4'])
        k.tt('dve', v3(yc[:]), v3(yc[:]), bc4(r4[:]), ALU.mult, ['yc', 'r4'], ['yc'])
        k.tt('pool', yc[:], yc[:], lngbc[:], ALU.mult, ['yc', VK[5]], ['yc'])
        k.tt('pool', yc[:], yc[:], lnbbc[:], ALU.add, ['yc', VK[6]], ['yc'])
        k.tt('dve', v3(tmpp[:]), v3(rd(vr[:])), bc4(bon[:]), ALU.mult, [kvr, kbon], ['tmpp'])
        k.tt('pool', yc[:], yc[:], tmpp[:], ALU.add, ['yc', 'tmpp'], ['yc'])
        k.tt('dve', ot[b][:], yc[:], gv[:], ALU.mult, ['yc', kgv], [f'ot{b}'])
        k.dma('pool', oc[rows, :], ot[b][:], r=[f'ot{b}'], final=True)

    gens = {}

    def adv(j):
        if 0 <= j < NT:
            try:
                next(gens[j])
            except StopIteration:
                pass

    for step in range(NT + 2):
        if step < NT:
            gens[step] = tile(step)
            adv(step)
        for r_i in range(3):
            adv(step - 1)
            adv(step - 2)
    return k.finish()


def rwkv_consts(CH=64):
    c = -math.exp(-0.5)
    blk = np.kron(np.eye(128 // CH), np.ones((CH, CH)))
    s_idx = np.arange(128)[:, None]
    t_idx = np.arange(128)[None, :]
    triw = np.stack([c * blk * (s_idx <= t_idx), c * blk * (s_idx < t_idx), c * blk * (s_idx > t_idx)]).astype(np.float32)
    lt_, le_, gt_ = blk * (s_idx < t_idx), blk * (s_idx <= t_idx), blk * (t_idx < s_idx)
    mask5 = np.concatenate([lt_, gt_, lt_, le_, le_], 1).astype(np.float32)
    rowm = np.stack([(np.arange(128) < 64), (np.arange(128) >= 64)], 1).astype(np.float32) if CH == 64 else np.ones((128, 2), np.float32)
    return dict(ident=np.eye(128, dtype=np.float32), triw=triw, mask5=mask5, rowm=rowm)


def rwkv_host_inputs(s, p_rwkv, prm, NH=4, CH=64):
    L = p_rwkv.shape[0]
    cs = slice(64 * NH * s, 64 * NH * (s + 1))
    r_, w1, k_, v_, a1, g1 = np.split(p_rwkv, np.cumsum([512, 64, 512, 512, 64])[:5], axis=-1)
    mu = prm['rwkv_mu']
    mur, muw1, muk, muv, mua1, mug1 = np.split(mu, np.cumsum([512, 64, 512, 512, 64])[:5])
    zm = np.zeros(64, np.float32)
    mul = np.concatenate([muw1, zm, mua1, zm, mug1]).reshape(3, 128).T
    vecs = np.stack([prm['rwkv_w0'][cs], prm['rwkv_a0'][cs], prm['rwkv_k_k'][cs], prm['rwkv_k_a'][cs],
                     prm['rwkv_r_k'].reshape(-1)[cs], prm['rwkv_ln_gain'][cs], prm['rwkv_ln_bias'][cs]])
    c_ = np.ascontiguousarray
    d = dict(pr=c_(r_[:, cs]), pk=c_(k_[:, cs]), pv=c_(v_[:, cs]),
             mu1=c_(np.concatenate([mur[cs], muk[cs], muv[cs]])),
             plw=c_(w1.T), pla=c_(a1.T), plg=c_(g1.T), mul=c_(mul),
             w2=c_(prm['rwkv_w2'][:, cs]), a2=c_(prm['rwkv_a2'][:, cs]),
             g2=c_(prm['rwkv_g2'][:, cs]), vecs=c_(vecs))
    d.update(rwkv_consts(CH))
    return d


FM0 = [(0, 128, 0), (128, 128, 128), (256, 128, 256), (384, 128, 384), (1536, 16, 512)] + \
      [(1552 + j * 128, 128, 528 + j * 128) for j in range(4)]
NF0 = 1040
FM1 = [(512, 64, 0), (1600, 64, 64), (1664, 128, 128)] + [(1792 + j * 128, 128, 256 + j * 128) for j in range(8)]
NF1 = 1280


def host_params(inp):
    c_ = lambda a: np.ascontiguousarray(np.asarray(a), dtype=np.float32)
    P = {}
    P['ident'] = np.eye(128, dtype=np.float32)
    P['triu'] = np.triu(np.ones((128, 128), np.float32))
    P['trigt'] = np.tril(np.ones((128, 128), np.float32), -1)
    for l in range(2):
        for j in range(7):
            P[f'g{l}_{j}'] = c_(inp['norm_gain'][l][j])
        for nm in ('xa_wq', 'xa_wk', 'xa_wv', 'xa_wo', 'mlp_w1', 'mlp_w2'):
            P[f'{nm}{l}'] = c_(inp[nm][l])
    P['w_in0'] = c_(inp['ab_w_in'][0])
    P['w_in1'] = c_(inp['cd_w_in'][0])
    P['w_out0'] = c_(inp['ab_w_out'][0])
    P['w_out1'] = c_(inp['cd_w_out'][0])
    P['wglu'] = c_(inp['s5_w_glu'][0])
    P['bglu'] = c_(inp['s5_b_glu'][0])
    prm0 = {k_: np.asarray(inp[k_][0]) for k_ in inp if k_.startswith('s5_') or k_.startswith('gla_')}
    prm1 = {k_: np.asarray(inp[k_][0]) for k_ in inp if k_.startswith('rwkv_') or k_.startswith('lru_')}
    for s in range(2):
        cs = slice(s * 128, (s + 1) * 128)
        P[f'gla_w2_{s}'] = c_(prm0['gla_w_decay2'][:, cs])
        P[f'gla_bd_{s}'] = c_(prm0['gla_b_decay'][None, cs])
        P[f'gla_gn_{s}'] = c_(prm0['gla_norm_gain'][2 * s:2 * s + 2].reshape(256))
        d = s5_host_inputs(s, np.zeros((2, 512), np.float32), prm0)
        for nm in ('lam_re', 'lam_im', 'lstep', 'Bre', 'Bim', 'Cre', 'Cim', 'dsk'):
            P[f's5_{nm}_{s}'] = c_(d[nm])
        P['iota_p'] = c_(d['iota_p'])
        P['iota_f'] = c_(d['iota_f'])
        if s == 0:
            d = rwkv_host_inputs(0, np.zeros((2, 1792), np.float32), prm1, 8, 64)
            for nm in ('mu1', 'mul', 'w2', 'a2', 'g2', 'vecs'):
                P[f'rw_{nm}'] = c_(d[nm])
            for nm in ('triw', 'mask5', 'rowm'):
                P[f'rw_{nm}'] = c_(d[nm])
        d = lru_host_inputs(s, np.zeros((2, 512), np.float32), np.zeros((2, 512), np.float32), prm1)
        for nm in ('cw', 'cb', 'Wa', 'Wx', 'ba', 'bx', 'lam'):
            P[f'lru_{nm}_{s}'] = c_(d[nm])
    return P


def build_fused(P, L):
    k = K(fused=True)
    X = {nm: k.xin(nm, a.shape) for nm, a in P.items()}
    x = k.xin('x', [L, D])
    mem = k.xin('mem', [256, D])
    out = k.xout('out', [L, D])
    proj0 = k.scratch('proj0', [L, 2064])
    PT0 = k.scratch('PT0', [NF0, L])
    proj1 = k.scratch('proj1', [L, 2816])
    PT1 = k.scratch('PT1', [NF1, L])
    o = k.scratch('o', [L, D])
    odT = k.scratch('odT', [512, L])
    h1 = k.scratch('h1', [L, D])
    h2 = k.scratch('h2', [L, D])
    h3 = k.scratch('h3', [L, D])

    def cblock(l, hin, hout, glu, ob_fm):
        io = dict(oa=o[:, 0:512], hin=hin, wout=X[f'w_out{l}'], g1=X[f'g{l}_1'], ident=X['ident'], hout=h1)
        if ob_fm:
            io['obT'] = odT
        else:
            io['ob'] = o[:, 512:1024]
        if glu:
            io.update(wglu=X['wglu'], bglu=X['bglu'])
        k.begin_phase(f'C1_{l}', io)
        build_C1(L, glu, k=k, ob_fm=ob_fm)
        k.begin_phase(f'C2_{l}', dict(hin=h1, mem=mem, wq=X[f'xa_wq{l}'], wk=X[f'xa_wk{l}'], wv=X[f'xa_wv{l}'], wo=X[f'xa_wo{l}'],
                                      g2=X[f'g{l}_2'], g3=X[f'g{l}_3'], g6=X[f'g{l}_6'], ident=X['ident'], hout=h2))
        build_C2(L, k=k)
        k.begin_phase(f'C3_{l}', dict(hin=h2, w1=X[f'mlp_w1{l}'], w2=X[f'mlp_w2{l}'], g4=X[f'g{l}_4'], g5=X[f'g{l}_5'],
                                      ident=X['ident'], hout=hout))
        build_C3(L, k=k)

    k.begin_phase('A0', dict(x=x, gain=X['g0_0'], W=X['w_in0'], ident=X['ident'], out=proj0, outT=PT0))
    build_A2(L, 2064, FM0, NF0, k=k)
    for s in range(2):
        io_g = dict(qT=PT0[s * 128:(s + 1) * 128, :], kT=PT0[256 + s * 128:256 + (s + 1) * 128, :],
                    ktok=proj0[:, 256 + s * 128:256 + (s + 1) * 128], v=proj0[:, 512 + s * 256:512 + (s + 1) * 256],
                    gate=proj0[:, 1024 + s * 256:1024 + (s + 1) * 256], dlrT=PT0[512:528, :],
                    w2=X[f'gla_w2_{s}'], bdec=X[f'gla_bd_{s}'], gn=X[f'gla_gn_{s}'], triu=X['triu'],
                    trigt=X['trigt'], oa=o[:, s * 256:(s + 1) * 256])
        k.begin_phase(f'GLA{s}', io_g)
        build_GLA(L, k=k)
    for s in range(2):
        io_s = dict(uT=PT0[528 + s * 256:528 + (s + 1) * 256, :], u=proj0[:, 1552 + s * 256:1552 + (s + 1) * 256],
                    triu=X['triu'], iota_p=X['iota_p'], iota_f=X['iota_f'], y=o[:, 512 + s * 256:512 + (s + 1) * 256])
        for nm in ('lam_re', 'lam_im', 'lstep', 'Bre', 'Bim', 'Cre', 'Cim', 'dsk'):
            io_s[nm] = X[f's5_{nm}_{s}']
        k.begin_phase(f'S5{s}', io_s)
        build_S5(L, k=k)
    cblock(0, x, h3, True, False)
    k.begin_phase('A1', dict(x=h3, gain=X['g1_0'], W=X['w_in1'], ident=X['ident'], out=proj1, outT=PT1))
    build_A2(L, 2816, FM1, NF1, k=k)
    io = dict(pr=proj1[:, 0:512], pk=proj1[:, 576:1088], pv=proj1[:, 1088:1600], plw=PT1[0:64, :], pla=PT1[64:128, :],
              plg=PT1[128:256, :], ident=X['ident'], triw=X['rw_triw'], mask5=X['rw_mask5'], rowm=X['rw_rowm'], oc=o[:, 0:512])
    for nm in ('mu1', 'mul', 'w2', 'a2', 'g2', 'vecs'):
        io[nm] = X[f'rw_{nm}']
    k.begin_phase('RW', io)
    build_RWKVP(L, k=k, CH=64)
    streams = []
    for s in range(2):
        io = dict(xbT=PT1[256 + s * 256:256 + (s + 1) * 256, :], gateT=PT1[768 + s * 256:768 + (s + 1) * 256, :],
                  odT=odT[s * 256:(s + 1) * 256, :])
        for nm in ('cw', 'cb', 'Wa', 'Wx', 'ba', 'bx', 'lam'):
            io[nm] = X[f'lru_{nm}_{s}']
        streams.append((f'l{s}_', io, lambda kk: gen_LRU(L, kk)))
    k.begin_phase('LRU', {})
    run_streams(k, streams)
    k.finish()
    cblock(1, h3, out, False, True)
    return k.finish_program()


BATCH, SEQ = 4, 4096
_CACHE = {}


def kernel(**inp):
    inp = {k_: np.asarray(v_) for k_, v_ in inp.items()}
    P = host_params(inp)
    if 'nc' not in _CACHE:
        _CACHE['nc'] = build_fused(P, SEQ)
    nc = _CACHE['nc']
    maps = []
    for b in range(BATCH):
        m = dict(P)
        m['x'] = np.ascontiguousarray(inp['x'][b], dtype=np.float32)
        m['mem'] = np.ascontiguousarray(inp['mem'][b], dtype=np.float32)
        maps.append(m)
    res = run_bass_kernel_spmd(nc, maps, core_ids=list(range(BATCH))).results
    return np.ascontiguousarray(np.stack([res[b]['out'] for b in range(BATCH)]).astype(np.float32))
```

```python
import os
import math
from contextlib import ExitStack


import numpy as np
import concourse.bass as bass
import concourse.mybir as mybir
from concourse.bass_utils import run_bass_kernel_spmd

F32 = mybir.dt.float32
BF16 = mybir.dt.bfloat16
I32 = mybir.dt.int32
AF = mybir.ActivationFunctionType
ALU = mybir.AluOpType
AX = mybir.AxisListType

ENGS = ['pe', 'act', 'dve', 'pool', 'sp']
NDMA_SLOTS = 8
SAME_ENGINE_SYNC = os.environ.get("NOSELF", "0") != "1"


class Prog:
    def __init__(self, nc):
        self.nc = nc
        self.ops = {e: [] for e in ENGS}
        self.cnt = {e: 0 for e in ENGS}
        self.last_w = {}
        self.readers = {}
        self.seen = {e: {} for e in ENGS}
        self.dma_n = {e: 0 for e in ENGS}
        self.dma_tok = {e: [None] * NDMA_SLOTS for e in ENGS}
        self.final_tokens = []
        from contextlib import ExitStack
        self.sem_stack = ExitStack()
        self.sems = {}
        for e in ['pe', 'act', 'dve', 'pool']:
            self.sems[('c', e)] = self.sem_stack.enter_context(nc.semaphore("s_c_" + e))
        for q in ['sp', 'pool']:
            for sl in range(NDMA_SLOTS):
                self.sems[('d', q, sl)] = self.sem_stack.enter_context(nc.semaphore(f"s_d_{q}_{sl}"))

    def barrier(self):
        toks = []
        for e in ['pe', 'act', 'dve', 'pool']:
            if self.cnt[e] > 0:
                toks.append((('c', e), self.cnt[e]))
        for q in ENGS:
            for t in self.dma_tok[q]:
                if t is not None:
                    toks.append(t)
        for e in ENGS:
            waits = []
            for (sem, val) in toks:
                if sem == ('c', e):
                    continue
                if self.seen[e].get(sem, 0) >= val:
                    continue
                waits.append((sem, val))
                self.seen[e][sem] = val
            if waits:
                self.ops[e].append((waits, None, None))
        self.last_w = {}
        self.readers = {}

    def _deps(self, eng, reads, writes):
        toks = []
        for r in reads:
            t = self.last_w.get(r)
            if t is not None:
                toks.append(t)
        for w in writes:
            t = self.last_w.get(w)
            if t is not None:
                toks.append(t)
            toks.extend(self.readers.get(w, []))
        need = {}
        for (sem, val) in toks:
            if not SAME_ENGINE_SYNC and sem == ('c', eng):
                continue
            if sem == ('c', 'pe') and eng == 'pe':
                continue
            if self.seen[eng].get(sem, 0) >= val:
                continue
            if need.get(sem, 0) < val:
                need[sem] = val
        for sem, val in need.items():
            self.seen[eng][sem] = val
        return list(need.items())

    def _commit(self, tok, reads, writes):
        for w in writes:
            self.last_w[w] = tok
            self.readers[w] = []
        for r in reads:
            if r in writes:
                continue
            self.readers.setdefault(r, []).append(tok)

    def op(self, eng, fn, reads=(), writes=()):
        self.nrec = getattr(self, 'nrec', 0) + 1
        if self.nrec > int(os.environ.get("MAXOPS", "100000000")):
            return None
        kp = getattr(self, 'key_prefix', '')
        reads = [r if r.startswith('ps') else kp + r for r in reads]
        writes = [w if w.startswith('ps') else kp + w for w in writes]
        pk = getattr(self, 'ps_prefix', '')
        reads = [('ps' + pk + r[2:]) if r.startswith('ps') else r for r in reads]
        writes = [('ps' + pk + w[2:]) if w.startswith('ps') else w for w in writes]
        writes = list(writes) + [r for r in reads if r.startswith('ps') and r not in writes]
        waits = self._deps(eng, reads, writes)
        self.cnt[eng] += 1
        tok = (('c', eng), self.cnt[eng])
        self.ops[eng].append((waits, fn, tok))
        self._commit(tok, reads, writes)
        return tok

    def dma(self, q, out, in_, reads=(), writes=(), final=False, **kw):
        self.nrec = getattr(self, 'nrec', 0) + 1
        if self.nrec > int(os.environ.get("MAXOPS", "100000000")):
            return None
        kp = getattr(self, 'key_prefix', '')
        reads = [kp + r for r in reads]
        writes = [kp + w for w in writes]
        waits = self._deps(q, reads, writes)
        n = self.dma_n[q]
        slot = n % NDMA_SLOTS
        prev = self.dma_tok[q][slot]
        if prev is not None and self.seen[q].get(prev[0], 0) < prev[1]:
            waits.append(prev)
            self.seen[q][prev[0]] = prev[1]
        tok = (('d', q, slot), 16 * (n // NDMA_SLOTS + 1))
        self.dma_n[q] += 1
        self.dma_tok[q][slot] = tok

        def fn(e, out=out, in_=in_, kw=kw):
            return e.dma_start(out=out, in_=in_, **kw)
        self.ops[q].append((waits, fn, tok))
        self._commit(tok, reads, writes)
        if final:
            self.final_tokens.append(tok)
        return tok

    def emit(self, last=True):
        nc = self.nc
        sems = self.sems
        with nc.Block() as block:
            final = list(self.final_tokens) if last else []

            def run(e, name):
                for waits, fn, tok in self.ops[name]:
                    for (s, v) in waits:
                        e.wait_ge(sems[s], v)
                    if fn is None:
                        continue
                    inst = fn(e)
                    inc = 16 if tok[0][0] == 'd' else 1
                    inst.then_inc(sems[tok[0]], inc)
                if name == 'sp':
                    for (s, v) in final:
                        e.wait_ge(sems[s], v)
                self.ops[name] = []

            @block.tensor
            def _(e):
                run(e, 'pe')

            @block.scalar
            def _(e):
                run(e, 'act')

            @block.vector
            def _(e):
                run(e, 'dve')

            @block.gpsimd
            def _(e):
                run(e, 'pool')

            @block.sync
            def _(e):
                run(e, 'sp')
        if last:
            self.sem_stack.close()


D = 1024
KC = 8
EPS = 1e-6


class K:
    def __init__(self, fused=False):
        self.nc = bass.Bass("TRN2", target_bir_lowering=False)
        self.st = ExitStack()
        self.P = Prog(self.nc)
        self.n = 0
        self.fused = fused
        self.io = {}
        self.pfx = ""

    def begin_phase(self, name, io):
        self.pfx = name + "_"
        self.io = io
        self.st = ExitStack()
        for a in ('wstage', 'rr_cache', 'identf', 'identb'):
            if hasattr(self, a):
                delattr(self, a)

    def scratch(self, name, shape, dt=F32):
        return self.nc.dram_tensor(name, list(shape), dt, kind="Internal").ap()

    def xin(self, name, arr_shape, dt=F32):
        return self.nc.dram_tensor(name, list(arr_shape), dt, kind="ExternalInput").ap()

    def xout(self, name, arr_shape, dt=F32):
        return self.nc.dram_tensor(name, list(arr_shape), dt, kind="ExternalOutput").ap()

    def din(self, name, shape, dt=F32):
        if self.fused:
            ap = self.io[name]
            assert list(ap.shape) == list(shape), (name, ap.shape, shape)
            return ap
        return self.nc.dram_tensor(name, list(shape), dt, kind="ExternalInput").ap()

    def dout(self, name, shape, dt=F32):
        if self.fused:
            ap = self.io[name]
            assert list(ap.shape) == list(shape), (name, ap.shape, shape)
            return ap
        return self.nc.dram_tensor(name, list(shape), dt, kind="ExternalOutput").ap()

    def sb(self, name, shape, dt=F32):
        pers = getattr(self, 'persist', None)
        if pers is not None and (self.pfx + name) in pers:
            return pers[self.pfx + name]
        return self.st.enter_context(self.nc.sbuf_tensor(self.pfx + name, list(shape), dt))

    def push_scope(self, persistent):
        self.persist = getattr(self, 'persist', None) or {}
        for (name, shape, dt) in persistent:
            self.persist[self.pfx + name] = self.st.enter_context(self.nc.sbuf_tensor(self.pfx + name, list(shape), dt))
        self._st_saved = self.st
        self.st = ExitStack()

    def pop_scope(self):
        self.P.barrier()
        self.P.emit(last=False)
        self.st.close()
        self.st = self._st_saved

    def ps(self, name, shape, dt=F32):
        return self.st.enter_context(self.nc.psum_tensor(self.pfx + name, list(shape), dt))

    def finish(self, last=True):
        if self.fused:
            self.P.barrier()
            self.P.emit(last=False)
            self.st.close()
            return None
        self.P.emit()
        self.st.close()
        return self.nc

    def finish_program(self):
        self.P.emit(last=True)
        return self.nc

    def mm(self, out, lhsT, rhs, start, stop, r, w):
        self.P.op('pe', lambda e: e.matmul(out, lhsT=lhsT, rhs=rhs, start=start, stop=stop), reads=r, writes=w)

    def tr(self, out, in_, ident, r, w):
        self.P.op('pe', lambda e: e.transpose(out=out, in_=in_, identity=ident), reads=list(r) + ['ident'], writes=w)

    def act(self, out, in_, func, r, w, **kw):
        self.P.op('act', lambda e: e.activation(out=out, in_=in_, func=func, **kw), reads=r, writes=w)

    def tt(self, eng, out, in0, in1, op, r, w):
        self.P.op(eng, lambda e: e.tensor_tensor(out=out, in0=in0, in1=in1, op=op), reads=r, writes=w)

    def ts(self, eng, out, in0, s1, s2, op0, op1, r, w):
        if op1 is None:
            self.P.op(eng, lambda e: e.tensor_scalar(out=out, in0=in0, scalar1=s1, scalar2=None, op0=op0), reads=r, writes=w)
        else:
            self.P.op(eng, lambda e: e.tensor_scalar(out=out, in0=in0, scalar1=s1, scalar2=s2, op0=op0, op1=op1), reads=r, writes=w)

    def stt(self, out, in0, scalar, in1, op0, op1, r, w):
        self.P.op('dve', lambda e: e.scalar_tensor_tensor(out=out, in0=in0, scalar=scalar, in1=in1, op0=op0, op1=op1),
                  reads=r, writes=w)

    def cp(self, eng, out, in_, r, w):
        if eng == 'act':
            self.P.op('act', lambda e: e.copy(out=out, in_=in_), reads=r, writes=w)
        else:
            self.P.op(eng, lambda e: e.tensor_copy(out=out, in_=in_), reads=r, writes=w)

    def recip(self, out, in_, r, w):
        self.P.op('dve', lambda e: e.reciprocal(out=out, in_=in_), reads=r, writes=w)

    def memset(self, eng, ap, val, w):
        self.P.op(eng, lambda e: e.memset(ap, val), reads=[], writes=w)

    def dma(self, q, out, in_, r=(), w=(), final=False, **kw):
        self.P.dma(q, out, in_, reads=r, writes=w, final=final, **kw)

    def consts(self, ident_d):
        self.identf = self.sb("identf", [128, 128], F32)
        self.identb = self.sb("identb", [128, 128], BF16)
        self.dma('sp', self.identf[:], ident_d, w=['ident'])
        self.cp('dve', self.identb[:], self.identf[:], ['ident'], ['ident'])

    def gain_cols(self, name, g_d):
        t = self.sb(name, [128, KC], F32)
        self.dma('sp', t[:], g_d.rearrange("(kc p) -> p kc", p=128), w=[name], allow_slow_non_contiguous=True)
        return t

    def bcast_row(self, name, vec_d, n):
        t = self.sb(name, [128, n], F32)
        self.dma('sp', t[:], vec_d.partition_broadcast(128), w=[name])
        return t

    def load_weight(self, name, w_d, kchunks, ncols, gcol=None, gkey=None, stage_cols=2048, q='sp'):
        wb = self.sb(name, [128, kchunks, ncols], BF16)
        if not hasattr(self, 'wstage'):
            self.wstage = [self.sb(f"wstage{i}", [128, stage_cols], F32) for i in range(2)]
            self.wstage_n = 0
            self.wstage_cols = stage_cols
        sc = self.wstage_cols
        wv = w_d.rearrange("(kc p) n -> p kc n", p=128)
        for kc in range(kchunks):
            for c0 in range(0, ncols, sc):
                cw = min(sc, ncols - c0)
                b = self.wstage_n % 2
                self.wstage_n += 1
                stg = self.wstage[b]
                self.dma(q, stg[:, 0:cw], wv[:, kc, c0:c0 + cw], w=[f'wstage{b}'])
                eng = 'act' if (kc % 2 == 0) else 'dve'
                if gcol is not None:
                    if eng == 'act':
                        self.act(wb[:, kc, c0:c0 + cw], stg[:, 0:cw], AF.Copy, [f'wstage{b}', gkey], [f'{name}{kc}'],
                                 scale=gcol[:, kc:kc + 1])
                    else:
                        self.ts('dve', wb[:, kc, c0:c0 + cw], stg[:, 0:cw], gcol[:, kc:kc + 1], None, ALU.mult, None,
                                [f'wstage{b}', gkey], [f'{name}{kc}'])
                else:
                    self.cp(eng, wb[:, kc, c0:c0 + cw], stg[:, 0:cw], [f'wstage{b}'], [f'{name}{kc}'])
        return wb

    def rstd_of(self, x_ap, xkey, ss, rstd, junk, key, ncols=D):
        self.act(junk, x_ap, AF.Square, [xkey], ['junk', key + 'ss'], accum_out=ss)
        self.ts('dve', rstd, ss, 1.0 / ncols, EPS, ALU.mult, ALU.add, [key + 'ss'], [key])
        self.act(rstd, rstd, AF.Sqrt, [key], [key])
        self.recip(rstd, rstd, [key], [key])


def pipeline(make_gen, n):
    active = []
    for i in range(n):
        for g in list(active):
            try:
                next(g)
            except StopIteration:
                active.remove(g)
        g = make_gen(i)
        active.append(g)
        try:
            next(g)
        except StopIteration:
            active.remove(g)
    while active:
        for g in list(active):
            try:
                next(g)
            except StopIteration:
                active.remove(g)


def pipeline_gen(make_gen, n):
    active = []
    for i in range(n):
        for g in list(active):
            try:
                next(g)
            except StopIteration:
                active.remove(g)
        g = make_gen(i)
        active.append(g)
        try:
            next(g)
        except StopIteration:
            active.remove(g)
        yield
    while active:
        for g in list(active):
            try:
                next(g)
            except StopIteration:
                active.remove(g)
        yield


def run_streams(k, streams):
    base_pfx = k.pfx
    gens = []
    for (pf, io, gf) in streams:
        gens.append([pf, io, None, gf])
    active = list(gens)
    while active:
        for st in list(active):
            pf, io, g, gf = st
            k.pfx = base_pfx + pf
            k.P.key_prefix = pf
            k.P.ps_prefix = pf
            k.io = io
            try:
                if g is None:
                    st[2] = gf(k)
                    g = st[2]
                next(g)
            except StopIteration:
                active.remove(st)
    k.pfx = base_pfx
    k.P.key_prefix = ''
    k.P.ps_prefix = ''


GELU_C = 1.5957691216057308


def norm_T(k, xt, xkey, xn, xnkey, xT_dst, xTkey, psT, psTkey, ss, rstd, junk, key, evac_eng='act'):
    k.rstd_of(xt, xkey, ss, rstd, junk, key)
    k.ts('dve', xn, xt, rstd, None, ALU.mult, None, [xkey, key], [xnkey])
    for kc in range(KC):
        k.tr(psT[:, kc * 128:(kc + 1) * 128], xn[:, kc * 128:(kc + 1) * 128], k.identb[:], [xnkey], [psTkey])
    k.cp(evac_eng, xT_dst, psT[:].rearrange("p (k t) -> p k t", k=KC), [psTkey], [xTkey])


def post_norm_res(k, ps2, pskeys, ht, hkey, gbc, gkey, tmp2, tmpkeys, ss2, rstd, junk, key):
    for j in range(2):
        k.act(junk[:, 0:512], ps2[j], AF.Square, [pskeys[j]], ['junk', key + f'ss{j}'], accum_out=ss2[:, j:j + 1])
    k.tt('dve', ss2[:, 0:1], ss2[:, 0:1], ss2[:, 1:2], ALU.add, [key + 'ss0', key + 'ss1'], [key + 'ss0'])
    k.ts('dve', rstd, ss2[:, 0:1], 1.0 / D, EPS, ALU.mult, ALU.add, [key + 'ss0'], [key])
    k.act(rstd, rstd, AF.Sqrt, [key], [key])
    k.recip(rstd, rstd, [key], [key])
    for j in range(2):
        sl = slice(j * 512, (j + 1) * 512)
        k.stt(tmp2[j], ps2[j], rstd, gbc[:, sl], ALU.mult, ALU.mult, [pskeys[j], key, gkey], [tmpkeys[j]])
        k.tt('pool', ht[:, sl], ht[:, sl], tmp2[j], ALU.add, [tmpkeys[j], hkey], [hkey])


def build_C1(NTOK, glu, k=None, ob_fm=False):
    k = k or K()
    NT = NTOK // 128
    oa = k.din("oa", [NTOK, 512])
    if ob_fm:
        obT = k.din("obT", [512, NTOK])
    else:
        ob = k.din("ob", [NTOK, 512])
    hin = k.din("hin", [NTOK, D])
    wout = k.din("wout", [D, D])
    g1 = k.din("g1", [D])
    ident_d = k.din("ident", [128, 128])
    if glu:
        wglu = k.din("wglu", [512, 512])
        bglu = k.din("bglu", [512])
    hout = k.dout("hout", [NTOK, D])
    k.consts(ident_d)
    g1bc = k.bcast_row("g1bc", g1, D)
    Wout = k.load_weight("Wout", wout, KC, D, stage_cols=1024)
    if glu:
        Wglu = k.load_weight("Wglu", wglu, 4, 512)
        bgbc = k.bcast_row("bgbc", bglu, 512)

    def ring(nm, shape, n, dt=F32):
        return [k.sb(f"{nm}{j}", shape, dt) for j in range(n)]
    oc = ring("oc", [128, D], 10 if glu else 4)
    ocb = ring("ocb", [128, D], 3, BF16)
    oT = ring("oT", [128, KC, 128], 3, BF16)
    ht = ring("ht", [128, D], 4)
    mix = ring("mix", [128, D], 5)
    tmp = ring("tmp", [128, D], 3)
    ss2 = ring("ss2", [128, 2], 4)
    rstd = ring("rstd", [128, 1], 5)
    junk = k.sb("junk", [128, D], BF16)
    if ob_fm:
        obt = ring("obt", [128, 4, 128], 4)
    if glu:
        yb = ring("yb", [128, 512], 3, BF16)
        yT = ring("yT", [128, 4, 128], 3, BF16)
        t1 = ring("t1", [128, 512], 9)
        zs = ring("zs", [128, 512], 4)
        psTg = k.ps("psTg", [128, D], BF16)
        psG = k.ps("psG", [128, 512])
    psTm = [k.ps(f"psTm{j}", [128, D], BF16) for j in range(2)]
    psM = [k.ps(f"psM{j}", [128, 512]) for j in range(4)]

    def tile(i):
        rows = slice(i * 128, (i + 1) * 128)
        def T(lst, nm):
            j = i % len(lst)
            return lst[j], f'{nm}{j}'
        oc_, koc = T(oc, 'oc'); ocb_, kocb = T(ocb, 'ocb'); oT_, koT = T(oT, 'oT'); ht_, kht = T(ht, 'ht')
        mix_, kmix = T(mix, 'mix'); tmp_, ktmp = T(tmp, 'tmp'); ss_, kss = T(ss2, 'ss2'); rs_, krs = T(rstd, 'rstd')
        pm = [psM[2 * (i % 2)], psM[2 * (i % 2) + 1]]
        kpm = [f'psM{2 * (i % 2)}', f'psM{2 * (i % 2) + 1}']
        ptm, kptm = psTm[i % 2], f'psTm{i % 2}'
        kA, kB = koc + 'A', koc + 'B'
        k.dma('sp', oc_[:, 0:512], oa[rows, :], w=[kA])
        if ob_fm:
            obt_, kobt = T(obt, 'obt')
            k.dma('sp', obt_[:], obT[:, rows].rearrange("(a p) t -> p a t", p=128), w=[kobt])
        else:
            k.dma('sp', oc_[:, 512:1024], ob[rows, :], w=[kB])
        yield
        if glu:
            y = oc_[:, 512:1024]
            yb_, kyb = T(yb, 'yb'); yT_, kyT = T(yT, 'yT'); t1_, kt1 = T(t1, 't1'); zs_, kzs = T(zs, 'zs')
            k.cp('dve', yb_[:], y, [kB], [kyb])
            k.act(t1_[:], y, AF.Square, [kB], [kt1])
            k.act(t1_[:], t1_[:], AF.Copy, [kt1], [kt1], scale=0.044715, bias=1.0)
            yield
            for kc in range(4):
                k.tr(psTg[:, kc * 128:(kc + 1) * 128], yb_[:, kc * 128:(kc + 1) * 128], k.identb[:], [kyb], ['psTg'])
            k.tt('pool', t1_[:], t1_[:], y, ALU.mult, [kt1, kB], [kt1])
            yield
            k.cp('act', yT_[:], psTg[:, 0:512].rearrange("p (k t) -> p k t", k=4), ['psTg'], [kyT])
            k.act(t1_[:], t1_[:], AF.Sigmoid, [kt1], [kt1], scale=GELU_C)
            yield
            for kc in range(4):
                k.mm(psG[:], yT_[:, kc, :], Wglu[:, kc, :], kc == 0, kc == 3, [kyT, f'Wglu{kc}'], ['psG'])
            yield
            k.tt('dve', zs_[:], psG[:], bgbc[:], ALU.add, ['psG', 'bgbc'], [kzs])
            yield
            k.act(zs_[:], zs_[:], AF.Sigmoid, [kzs], [kzs])
            yield
            k.tt('dve', zs_[:], t1_[:], zs_[:], ALU.mult, [kt1, kzs], [kzs])
            k.tt('dve', y, y, zs_[:], ALU.mult, [kB, kzs], [kB])
        if ob_fm:
            k.cp('dve', ocb_[:, 0:512], oc_[:, 0:512], [kA], [kocb])
            k.cp('pool', oT_[:, 4:8, :], obt_[:], [kobt], [koT + 'b'])
        else:
            k.cp('dve', ocb_[:], oc_[:], [kA, kB], [kocb])
        yield
        nk = 4 if ob_fm else KC
        for kc in range(nk):
            k.tr(ptm[:, kc * 128:(kc + 1) * 128], ocb_[:, kc * 128:(kc + 1) * 128], k.identb[:], [kocb], [kptm])
        yield
        k.cp('act', oT_[:, 0:nk, :], ptm[:, 0:nk * 128].rearrange("p (k t) -> p k t", k=nk), [kptm], [koT])
        yield
        for cg in range(2):
            for kc in range(KC):
                ok_ = (koT + 'b') if (ob_fm and kc >= 4) else koT
                k.mm(pm[cg][:], oT_[:, kc, :], Wout[:, kc, cg * 512:(cg + 1) * 512], kc == 0, kc == KC - 1,
                     [ok_, f'Wout{kc}'], [kpm[cg]])
        yield
        for j in range(2):
            k.act(junk[:, 0:512], pm[j][:], AF.Square, [kpm[j]], ['junk', kss], accum_out=ss_[:, j:j + 1])
        for j in range(2):
            k.cp('act', mix_[:, j * 512:(j + 1) * 512], pm[j][:], [kpm[j]], [kmix])
        k.dma('sp', ht_[:], hin[rows, :], w=[kht])
        yield
        k.tt('dve', ss_[:, 0:1], ss_[:, 0:1], ss_[:, 1:2], ALU.add, [kss], [kss])
        k.ts('dve', rs_[:], ss_[:, 0:1], 1.0 / D, EPS, ALU.mult, ALU.add, [kss], [krs])
        yield
        k.act(rs_[:], rs_[:], AF.Sqrt, [krs], [krs])
        yield
        k.recip(rs_[:], rs_[:], [krs], [krs])
        k.stt(tmp_[:], mix_[:], rs_[:], g1bc[:], ALU.mult, ALU.mult, [kmix, krs, 'g1bc'], [ktmp])
        yield
        k.tt('pool', ht_[:], ht_[:], tmp_[:], ALU.add, [kht, ktmp], [kht])
        k.dma('pool', hout[rows, :], ht_[:], r=[kht], final=True)

    pipeline(tile, NT)
    return k.finish()


def build_C3(NTOK, k=None):
    k = k or K()
    NB = NTOK // 512
    DFF = 4096
    FC = DFF // 128
    hin = k.din("hin", [NTOK, D])
    w1 = k.din("w1", [D, DFF])
    w2 = k.din("w2", [DFF, D])
    g4 = k.din("g4", [D])
    g5 = k.din("g5", [D])
    ident_d = k.din("ident", [128, 128])
    hout = k.dout("hout", [NTOK, D])
    k.consts(ident_d)
    g4c = k.gain_cols("g4c", g4)
    g5bc = k.bcast_row("g5bc", g5, D)
    W1 = k.load_weight("W1", w1, KC, DFF, gcol=g4c, gkey='g4c', stage_cols=512)
    W2 = k.load_weight("W2", w2, FC, D, stage_cols=512)
    ht = [k.sb(f"ht{i}", [128, D]) for i in range(4)]
    xn = [k.sb(f"xn{i}", [128, D], BF16) for i in range(2)]
    xT = k.sb("xT", [128, KC, 512], BF16)
    AT = k.sb("AT", [128, FC, 512], BF16)
    sq = [k.sb(f"sq{i}", [128, 512]) for i in range(2)]
    junk = k.sb("junk", [128, D], BF16)
    ss = [k.sb(f"ss{i}", [128, 1]) for i in range(2)]
    ss2 = [k.sb(f"ss2{i}", [128, 2]) for i in range(2)]
    rstd = [k.sb(f"rstd{i}", [128, 1]) for i in range(2)]
    rstd2 = [k.sb(f"rstdb{i}", [128, 1]) for i in range(2)]
    psT = k.ps("psT", [128, D], BF16)
    psU = [k.ps(f"psU{i}", [128, 512]) for i in range(3)]
    psD = [k.ps(f"psD{i}", [128, 512]) for i in range(4)]
    nu = 0
    for blk in range(NB):
        for tt in range(4):
            i = blk * 4 + tt
            b = i % 2
            rows = slice(i * 128, (i + 1) * 128)
            k.dma('sp', ht[tt][:], hin[rows, :], w=[f'ht{tt}'])
            norm_T(k, ht[tt][:], f'ht{tt}', xn[b][:], f'xn{b}', xT[:, :, tt * 128:(tt + 1) * 128], 'xT', psT[:], 'psT',
                   ss[b][:], rstd[b][:], junk[:], f'n{b}')
        for fc in range(FC):
            pu = nu % 3
            nu += 1
            for kc in range(KC):
                k.mm(psU[pu][:], W1[:, kc, fc * 128:(fc + 1) * 128], xT[:, kc, :], kc == 0, kc == KC - 1,
                     [f'W1{kc}', 'xT'], [f'psU{pu}'])
            sb_ = fc % 2
            k.act(sq[sb_][:], psU[pu][:], AF.Square, [f'psU{pu}'], [f'sq{sb_}'])
            k.stt(AT[:, fc, :], psU[pu][:], 0.0, sq[sb_][:], ALU.is_gt, ALU.mult, [f'psU{pu}', f'sq{sb_}'], ['AT'])
        for tt in range(4):
            i = blk * 4 + tt
            b = i % 2
            rows = slice(i * 128, (i + 1) * 128)
            for cg in range(2):
                pd = 2 * b + cg
                for fc in range(FC):
                    k.mm(psD[pd][:], AT[:, fc, tt * 128:(tt + 1) * 128], W2[:, fc, cg * 512:(cg + 1) * 512],
                         fc == 0, fc == FC - 1, ['AT', f'W2{fc}'], [f'psD{pd}'])
            post_norm_res(k, [psD[2 * b][:], psD[2 * b + 1][:]], [f'psD{2 * b}', f'psD{2 * b + 1}'], ht[tt], f'ht{tt}',
                          g5bc, 'g5bc', [sq[0][:], sq[1][:]], ['sq0', 'sq1'], ss2[b], rstd2[b][:], junk, f'pn{b}')
            k.dma('pool', hout[rows, :], ht[tt][:], r=[f'ht{tt}'], final=True)
    return k.finish()


def build_C2(NTOK, k=None):
    k = k or K()
    NB = NTOK // 512
    MEM = 256
    hin = k.din("hin", [NTOK, D])
    mem = k.din("mem", [MEM, D])
    wq = k.din("wq", [D, D])
    wk = k.din("wk", [D, D])
    wv = k.din("wv", [D, D])
    wo = k.din("wo", [D, D])
    g2 = k.din("g2", [D])
    g3 = k.din("g3", [D])
    g6 = k.din("g6", [D])
    ident_d = k.din("ident", [128, 128])
    hout = k.dout("hout", [NTOK, D])
    k.consts(ident_d)
    g2c = k.gain_cols("g2c", g2)
    g6c = k.gain_cols("g6c", g6)
    g3bc = k.bcast_row("g3bc", g3, D)
    Wk = k.load_weight("Wk", wk, KC, D, gcol=g6c, gkey='g6c', stage_cols=1024)
    Wv = k.load_weight("Wv", wv, KC, D, gcol=g6c, gkey='g6c', stage_cols=1024)
    Wq = k.load_weight("Wq", wq, KC, D, gcol=g2c, gkey='g2c', stage_cols=1024)
    Wo = k.load_weight("Wo", wo, KC, D, stage_cols=1024)
    ht = [k.sb(f"ht{i}", [128, D]) for i in range(8)]
    xn = [k.sb(f"xn{i}", [128, D], BF16) for i in range(2)]
    xT = [k.sb(f"xT{i}", [128, KC, 512], BF16) for i in range(2)]
    memT = k.sb("memT", [128, KC, MEM], BF16)
    KT = k.sb("KT", [128, KC, MEM], BF16)
    V = k.sb("V", [128, 2, D], BF16)
    QT = [k.sb(f"QT{i}", [128, KC, 512], BF16) for i in range(2)]
    Pm = [k.sb(f"Pm{i}", [128, 4, MEM], BF16) for i in range(3)]
    Pn = [k.sb(f"Pn{i}", [128, 4, MEM], BF16) for i in range(3)]
    PT = [k.sb(f"PT{i}", [128, 8, 128], BF16) for i in range(3)]
    OT = [k.sb(f"OT{i}", [128, KC, 128], BF16) for i in range(3)]
    tmp = [k.sb(f"tmp{i}", [128, 512]) for i in range(2)]
    junk = k.sb("junk", [128, D], BF16)
    ss = [k.sb(f"ss{i}", [128, 1]) for i in range(2)]
    ss2 = [k.sb(f"ss2{i}", [128, 2]) for i in range(2)]
    rstd = [k.sb(f"rstd{i}", [128, 1]) for i in range(2)]
    rstd2 = [k.sb(f"rstdb{i}", [128, 1]) for i in range(2)]
    mx = [k.sb(f"mx{i}", [128, 4]) for i in range(3)]
    sm = [k.sb(f"sm{i}", [128, 4]) for i in range(3)]
    psT = k.ps("psT", [128, D], BF16)
    psA = k.ps("psA", [128, 1024])
    psS = k.ps("psS", [128, 1024])
    psX = k.ps("psX", [128, 1024])
    for mt in range(2):
        k.dma('sp', ht[mt][:], mem[mt * 128:(mt + 1) * 128, :], w=[f'ht{mt}'])
        norm_T(k, ht[mt][:], f'ht{mt}', xn[mt][:], f'xn{mt}', memT[:, :, mt * 128:(mt + 1) * 128], 'memT', psT[:], 'psT',
               ss[mt][:], rstd[mt][:], junk[:], f'n{mt}')
    for cc in range(KC):
        pa = cc % 2
        for kc in range(KC):
            k.mm(psA[:, pa * 512:pa * 512 + MEM], Wk[:, kc, cc * 128:(cc + 1) * 128], memT[:, kc, :], kc == 0, kc == KC - 1,
                 [f'Wk{kc}', 'memT'], [f'psA{pa}'])
        k.cp('act' if cc % 2 else 'dve', KT[:, cc, :], psA[:, pa * 512:pa * 512 + MEM], [f'psA{pa}'], [f'KT{cc}'])
    for mt in range(2):
        for cg in range(2):
            for kc in range(KC):
                k.mm(psX[:, cg * 512:(cg + 1) * 512], memT[:, kc, mt * 128:(mt + 1) * 128], Wv[:, kc, cg * 512:(cg + 1) * 512],
                     kc == 0, kc == KC - 1, ['memT', f'Wv{kc}'], [f'psX{cg}'])
            k.cp('act' if cg else 'dve', V[:, mt, cg * 512:(cg + 1) * 512], psX[:, cg * 512:(cg + 1) * 512], [f'psX{cg}'], [f'V{mt}{cg}'])
    def tile(i):
        blk, tt = divmod(i, 4)
        xb = blk % 2
        b = i % 3
        rows = slice(i * 128, (i + 1) * 128)
        tsl = slice(tt * 128, (tt + 1) * 128)
        hb = xb * 4 + tt
        if tt == 0:
            for t2_ in range(4):
                i2 = blk * 4 + t2_
                b2 = i2 % 2
                hb2 = xb * 4 + t2_
                k.dma('sp', ht[hb2][:], hin[i2 * 128:(i2 + 1) * 128, :], w=[f'ht{hb2}'])
                norm_T(k, ht[hb2][:], f'ht{hb2}', xn[b2][:], f'xn{b2}', xT[xb][:, :, t2_ * 128:(t2_ + 1) * 128], f'xT{xb}', psT[:], 'psT',
                       ss[b2][:], rstd[b2][:], junk[:], f'n{b2}')
            for cc in range(KC):
                pa = cc % 2
                for kc in range(KC):
                    k.mm(psA[:, pa * 512:(pa + 1) * 512], Wq[:, kc, cc * 128:(cc + 1) * 128], xT[xb][:, kc, :], kc == 0, kc == KC - 1,
                         [f'Wq{kc}', f'xT{xb}'], [f'psA{pa}'])
                k.cp('act' if cc % 2 else 'dve', QT[xb][:, cc, :], psA[:, pa * 512:(pa + 1) * 512], [f'psA{pa}'], [f'QT{xb}{cc}'])
        for h in range(4):
            sb_ = h // 2
            for j in range(2):
                cc = 2 * h + j
                k.mm(psS[:, h * MEM:(h + 1) * MEM], QT[xb][:, cc, tsl], KT[:, cc, :], j == 0, j == 1,
                     [f'QT{xb}{cc}', f'KT{cc}'], [f'psS{sb_}'])
        k.P.op('dve', lambda e, b=b: e.tensor_reduce(out=mx[b][:], in_=psS[:].rearrange("p (h m) -> p h m", h=4),
                                                    axis=AX.X, op=ALU.max),
               reads=['psS0', 'psS1'], writes=[f'mx{b}'])
        k.ts('dve', mx[b][:], mx[b][:], -1.0 / 16.0, None, ALU.mult, None, [f'mx{b}'], [f'mx{b}'])
        for h in range(4):
            k.act(Pm[b][:, h, :], psS[:, h * MEM:(h + 1) * MEM], AF.Exp, [f'psS{h // 2}', f'mx{b}'], [f'Pm{b}', f'sm{b}'],
                  scale=1.0 / 16.0, bias=mx[b][:, h:h + 1], accum_out=sm[b][:, h:h + 1])
        k.recip(sm[b][:], sm[b][:], [f'sm{b}'], [f'sm{b}'])
        k.tt('dve', Pn[b][:], Pm[b][:], sm[b][:].unsqueeze(2).broadcast_to([128, 4, MEM]), ALU.mult,
             [f'Pm{b}', f'sm{b}'], [f'Pn{b}'])
        yield
        for h in range(4):
            for mt in range(2):
                k.tr(psT[:, (h * 2 + mt) * 128:(h * 2 + mt + 1) * 128], Pn[b][:, h, mt * 128:(mt + 1) * 128], k.identb[:],
                     [f'Pn{b}'], ['psT'])
        k.cp('act', PT[b][:], psT[:].rearrange("p (k t) -> p k t", k=8), ['psT'], [f'PT{b}'])
        for cc in range(KC):
            h = cc // 2
            pa = cc // 4
            for mt in range(2):
                k.mm(psA[:, cc * 128:(cc + 1) * 128], V[:, mt, cc * 128:(cc + 1) * 128], PT[b][:, h * 2 + mt, :],
                     mt == 0, mt == 1, [f'V{mt}{cc // 4}', f'PT{b}'], [f'psA{pa}'])
        k.cp('dve', OT[b][:, 0:4, :], psA[:, 0:512].rearrange("p (k t) -> p k t", k=4), ['psA0'], [f'OT{b}_0'])
        k.cp('act', OT[b][:, 4:8, :], psA[:, 512:1024].rearrange("p (k t) -> p k t", k=4), ['psA1'], [f'OT{b}_1'])
        yield
        for cg in range(2):
            for cc in range(KC):
                k.mm(psX[:, cg * 512:(cg + 1) * 512], OT[b][:, cc, :], Wo[:, cc, cg * 512:(cg + 1) * 512],
                     cc == 0, cc == KC - 1, [f'OT{b}_{cc // 4}', f'Wo{cc}'], [f'psX{cg}'])
        post_norm_res(k, [psX[:, 0:512], psX[:, 512:1024]], ['psX0', 'psX1'], ht[hb], f'ht{hb}',
                      g3bc, 'g3bc', [tmp[0][:], tmp[1][:]], ['tmp0', 'tmp1'], ss2[b % 2], rstd2[b % 2][:], junk, f'pn{b % 2}')
        k.dma('pool', hout[rows, :], ht[hb][:], r=[f'ht{hb}'], final=True)

    pipeline(tile, NTOK // 128)
    return k.finish()


def build_A2(NTOK, NC, fm, NF, k=None):
    k = k or K()
    NB = NTOK // 512
    x = k.din("x", [NTOK, D])
    gain = k.din("gain", [D])
    W = k.din("W", [D, NC])
    ident_d = k.din("ident", [128, 128])
    out = k.dout("out", [NTOK, NC])
    outT = k.dout("outT", [NF, NTOK])
    k.consts(ident_d)
    gc = k.gain_cols("gc", gain)
    Wb = k.load_weight("Wb", W, KC, NC, gcol=gc, gkey='gc', stage_cols=1408)
    cgs = [(c0, min(512, NC - c0)) for c0 in range(0, NC, 512)]
    xt = [k.sb(f"xt{i}", [128, D]) for i in range(2)]
    xn = [k.sb(f"xn{i}", [128, D], BF16) for i in range(2)]
    xT = [k.sb(f"xT{i}", [128, KC, 512], BF16) for i in range(2)]
    ot = [k.sb(f"ot{i}", [128, NC]) for i in range(2)]
    ft = [k.sb(f"ft{i}", [128, 512]) for i in range(2)]
    junk = k.sb("junk", [128, D], BF16)
    ss = [k.sb(f"ss{i}", [128, 1]) for i in range(2)]
    rstd = [k.sb(f"rstd{i}", [128, 1]) for i in range(2)]
    psT = k.ps("psT", [128, D], BF16)
    psO = [k.ps(f"psO{i}", [128, 512]) for i in range(4)]
    psF = [k.ps(f"psF{i}", [128, 512]) for i in range(2)]
    no = 0
    nf = 0
    for blk in range(NB):
        xb = blk % 2
        for tt in range(4):
            i = blk * 4 + tt
            b = i % 2
            k.dma('sp', xt[b][:], x[i * 128:(i + 1) * 128, :], w=[f'xt{b}'])
            norm_T(k, xt[b][:], f'xt{b}', xn[b][:], f'xn{b}', xT[xb][:, :, tt * 128:(tt + 1) * 128], f'xT{xb}', psT[:], 'psT',
                   ss[b][:], rstd[b][:], junk[:], f'n{b}')
        for tt in range(4):
            i = blk * 4 + tt
            b = i % 2
            for ci, (c0, cw) in enumerate(cgs):
                pb = no % 4
                no += 1
                for kc in range(KC):
                    k.mm(psO[pb][:, 0:cw], xT[xb][:, kc, tt * 128:(tt + 1) * 128], Wb[:, kc, c0:c0 + cw], kc == 0, kc == KC - 1,
                         [f'xT{xb}', f'Wb{kc}'], [f'psO{pb}'])
                k.cp('dve' if pb % 2 == 0 else 'act', ot[b][:, c0:c0 + cw], psO[pb][:, 0:cw], [f'psO{pb}'], [f'ot{b}_{pb % 2}'])
            k.dma('pool', out[i * 128:(i + 1) * 128, :], ot[b][:], r=[f'ot{b}_0', f'ot{b}_1'], final=True)
        for (c0, cw, r0) in fm:
            pf = nf % 2
            nf += 1
            for kc in range(KC):
                k.mm(psF[pf][0:cw, :], Wb[:, kc, c0:c0 + cw], xT[xb][:, kc, :], kc == 0, kc == KC - 1,
                     [f'Wb{kc}', f'xT{xb}'], [f'psF{pf}'])
            k.cp('dve' if pf == 0 else 'act', ft[pf][0:cw, :], psF[pf][0:cw, :], [f'psF{pf}'], [f'ft{pf}'])
            k.dma('pool', outT[r0:r0 + cw, blk * 512:(blk + 1) * 512], ft[pf][0:cw, :], r=[f'ft{pf}'], final=True)
    return k.finish()


def gen_GLA(L, k):
    NT = L // 128
    qT = k.din("qT", [128, L])
    kT = k.din("kT", [128, L])
    ktok = k.din("ktok", [L, 128])
    v = k.din("v", [L, 256])
    gate = k.din("gate", [L, 256])
    dlrT = k.din("dlrT", [16, L])
    w2 = k.din("w2", [16, 128])
    bdec = k.din("bdec", [1, 128])
    gn = k.din("gn", [256])
    triu_d = k.din("triu", [128, 128])
    trigt_d = k.din("trigt", [128, 128])
    oa = k.dout("oa", [L, 256])

    triu = k.sb("triu_s", [128, 128])
    trigt = k.sb("trigt_s", [128, 128])
    k.dma('sp', triu[:], triu_d, w=['triu'])
    k.dma('sp', trigt[:], trigt_d, w=['trigt'])
    w2s = k.sb("w2s", [16, 128])
    k.dma('sp', w2s[:], w2, w=['w2s'])
    bds = k.sb("bds", [1, 128])
    k.dma('sp', bds[:], bdec, w=['bds'])
    ones1 = k.sb("ones1", [1, 128])
    k.memset('dve', ones1[:], 1.0, ['ones1'])
    gnbc = k.bcast_row("gnbc", gn, 256)
    S = k.sb("S", [128, 128])
    k.memset('dve', S[:], 0.0, ['S'])
    rm = k.sb("rm", [128, 2])
    k.memset('dve', rm[:], 0.0, ['rm'])
    k.memset('dve', rm[0:64, 0:1], 0.125, ['rm'])
    k.memset('dve', rm[64:128, 1:2], 0.125, ['rm'])

    def ring(nm, shape, n, dt=F32):
        return [k.sb(f"{nm}{j}", shape, dt) for j in range(n)]
    qTt, kTt, kt, gt = ring("qTt", [128, 128], 8), ring("kTt", [128, 128], 8), ring("kt", [128, 128], 8), ring("gt", [128, 256], 8)
    vt = ring("vt", [128, 256], 11)
    dt_ = ring("dt", [16, 128], 3)
    la = ring("la", [128, 128], 4)
    sg = ring("sg", [128, 256], 16)
    EqT, EkT, Eks = ring("EqT", [128, 128], 7), ring("EkT", [128, 128], 3), ring("Eks", [128, 128], 3)
    qin, kin, kst = ring("qin", [128, 2, 128], 5), ring("kin", [128, 128], 3), ring("kst", [128, 128], 5)
    sc0, sc1 = ring("sc0_", [128, 128], 3), ring("sc1_", [128, 128], 3)
    osr = ring("osr", [128, 256], 6)
    osb = ring("osb", [128, 256], 3)
    ss, rs = ring("ss", [128, 2], 4), ring("rs", [128, 2], 5)
    ot = ring("ot", [128, 256], 3)
    junk = k.sb("junk", [128, 128])
    psZ = [k.ps(f"psZ{j}", [128, 512]) for j in range(2)]
    psA = [k.ps(f"psA{j}", [128, 512]) for j in range(2)]
    psB = [k.ps(f"psB{j}", [128, 512]) for j in range(2)]
    psC = [k.ps(f"psC{j}", [128, 512]) for j in range(2)]

    def tile(i):
        rows = slice(i * 128, (i + 1) * 128)
        R = lambda lst: (lst[i % len(lst)], f'{lst[0].name if hasattr(lst[0], "name") else id(lst)}_{i % len(lst)}')
        def T(lst, nm):
            j = i % len(lst)
            return lst[j], f'{nm}{j}'
        q_, kq = T(qTt, 'qTt'); kT_, kkT = T(kTt, 'kTt'); kt_, kkt = T(kt, 'kt'); v_, kv = T(vt, 'vt'); g_, kg = T(gt, 'gt')
        d_, kd = T(dt_, 'dt'); la_, kla = T(la, 'la'); sg_, ksg = T(sg, 'sg')
        Eq, kEq = T(EqT, 'EqT'); Ek, kEk = T(EkT, 'EkT'); Es, kEs = T(Eks, 'Eks')
        qi, kqi = T(qin, 'qin'); ki, kki = T(kin, 'kin'); ks, kks = T(kst, 'kst')
        scs = [T(sc0, 'sc0_'), T(sc1, 'sc1_')]
        orw, korw = T(osr, 'osr'); ob_, kob = T(osb, 'osb'); ss_, kss = T(ss, 'ss'); rs_, krs = T(rs, 'rs'); ot_, kot = T(ot, 'ot')
        pz, kpz = psZ[i % 2], f'psZ{i % 2}'
        pa, kpa = psA[i % 2], f'psA{i % 2}'
        pb, kpb = psB[i % 2], f'psB{i % 2}'
        pc, kpc = psC[i % 2], f'psC{i % 2}'
        k.dma('sp', q_[:], qT[:, rows], w=[kq])
        k.dma('sp', kT_[:], kT[:, rows], w=[kkT])
        k.dma('sp', kt_[:], ktok[rows, :], w=[kkt])
        k.dma('sp', v_[:], v[rows, :], w=[kv])
        k.dma('sp', g_[:], gate[rows, :], w=[kg])
        k.dma('sp', d_[:], dlrT[:, rows], w=[kd])
        yield
        k.mm(pz[:, 0:128], d_[:], w2s[:], True, False, [kd, 'w2s'], [kpz])
        k.mm(pz[:, 0:128], ones1[:], bds[:], False, True, ['ones1', 'bds'], [kpz])
        yield
        k.act(la_[:], pz[:, 0:128], AF.Exp, [kpz], [kla], scale=-1.0)
        k.act(la_[:], la_[:], AF.Ln, [kla], [kla], bias=1.0)
        k.act(sg_[:], g_[:], AF.Exp, [kg], [ksg], scale=-1.0)
        yield
        k.ts('dve', la_[:], la_[:], -1.0 / 16.0, None, ALU.mult, None, [kla], [kla])
        k.ts('dve', sg_[:], sg_[:], 1.0, None, ALU.add, None, [ksg], [ksg])
        k.recip(sg_[:], sg_[:], [ksg], [ksg])
        yield
        k.mm(pa[:, 0:128], la_[:], triu[:], True, True, [kla, 'triu'], [kpa])
        k.mm(pa[:, 128:256], trigt[:], la_[:], True, True, [kla, 'trigt'], [kpa])
        yield
        k.act(Eq[:], pa[:, 0:128], AF.Exp, [kpa], [kEq])
        k.act(Ek[:], pa[:, 0:128], AF.Exp, [kpa], [kEk], scale=-1.0)
        k.act(Es[:], pa[:, 128:256], AF.Exp, [kpa], [kEs])
        yield
        for h in range(2):
            k.stt(qi[:, h, :], q_[:], rm[:, h:h + 1], Eq[:], ALU.mult, ALU.mult, [kq, kEq, 'rm'], [kqi])
        k.tt('pool', ki[:], kT_[:], Ek[:], ALU.mult, [kkT, kEk], [kki])
        k.tt('pool', ks[:], kt_[:], Es[:], ALU.mult, [kkt, kEs], [kks])
        k.tt('pool', sg_[:], sg_[:], g_[:], ALU.mult, [ksg, kg], [ksg])
        yield
        for h in range(2):
            hp = slice(h * 64, (h + 1) * 64)
            k.mm(pb[:, h * 128:(h + 1) * 128], ki[:], qi[:, h, :], True, True, [kki, kqi], [kpb])
        yield
        for h in range(2):
            k.tt('dve', scs[h][0][:], pb[:, h * 128:(h + 1) * 128], triu[:], ALU.mult, [kpb, 'triu'], [scs[h][1]])
        yield
        for h in range(2):
            hp = slice(h * 64, (h + 1) * 64)
            k.mm(pc[:, h * 128:(h + 1) * 128], scs[h][0][:], v_[:, h * 128:(h + 1) * 128], True, False, [scs[h][1], kv], [kpc])
            k.mm(pc[:, h * 128:(h + 1) * 128], qi[:, h, :], S[:], False, True, [kqi, 'S'], [kpc])
        k.mm(pc[:, 256:512], ks[:], v_[:], True, True, [kks, kv], [kpc])
        yield
        for h in range(2):
            hp = slice(h * 64, (h + 1) * 64)
            k.stt(S[hp, :], S[hp, :], Eq[hp, 127:128], pc[hp, 256 + h * 128:256 + (h + 1) * 128], ALU.mult, ALU.add,
                  ['S', kEq, kpc], ['S'])
        k.cp('act', orw[:], pc[:, 0:256], [kpc], [korw])
        yield
        for h in range(2):
            k.act(junk[:], orw[:, h * 128:(h + 1) * 128], AF.Square, [korw], ['junk', kss], accum_out=ss_[:, h:h + 1])
        yield
        k.ts('dve', rs_[:], ss_[:], 1.0 / 128.0, EPS, ALU.mult, ALU.add, [kss], [krs])
        yield
        k.act(rs_[:], rs_[:], AF.Ln, [krs], [krs])
        k.act(rs_[:], rs_[:], AF.Exp, [krs], [krs], scale=-0.5)
        yield
        for h in range(2):
            hs = slice(h * 128, (h + 1) * 128)
            k.stt(ob_[:, hs], orw[:, hs], rs_[:, h:h + 1], gnbc[:, hs], ALU.mult, ALU.mult, [korw, krs, 'gnbc'], [kob])
        yield
        k.tt('pool', ot_[:], ob_[:], sg_[:], ALU.mult, [kob, ksg], [kot])
        k.dma('pool', oa[rows, :], ot_[:], r=[kot], final=True)

    yield from pipeline_gen(tile, NT)


def build_GLA(L, k=None):
    k = k or K()
    for _ in gen_GLA(L, k):
        pass
    return k.finish()


TWO_PI = 2.0 * math.pi
C1 = 6.28125
C2 = TWO_PI - 6.28125
PI_LO = 3.1415925


def range_sincos(k, x, xkey, shape, s_out, c_out, skey, ckey, pfx):
    if not hasattr(k, 'rr_cache'):
        k.rr_cache = {}
    if pfx not in k.rr_cache:
        k.rr_cache[pfx] = (k.sb(pfx + "kf", shape), k.sb(pfx + "ki", shape, I32), k.sb(pfx + "r", shape), k.sb(pfx + "m", shape))
    kf, ki, r, m = k.rr_cache[pfx]
    a = lambda t: t[:]
    K1, K2, K3, K4 = pfx + 'kf', pfx + 'ki', pfx + 'r', pfx + 'm'
    k.ts('dve', a(kf), x, 1.0 / TWO_PI, None, ALU.mult, None, [xkey], [K1])
    k.cp('dve', a(ki), a(kf), [K1], [K2])
    k.cp('dve', a(kf), a(ki), [K2], [K1])
    k.stt(a(r), a(kf), -C1, x, ALU.mult, ALU.add, [K1, xkey], [K3])
    k.stt(a(r), a(kf), -C2, a(r), ALU.mult, ALU.add, [K1, K3], [K3])
    k.ts('dve', a(m), a(r), math.pi, -TWO_PI, ALU.is_gt, ALU.mult, [K3], [K4])
    k.tt('dve', a(r), a(r), a(m), ALU.add, [K3, K4], [K3])
    k.ts('dve', a(m), a(r), -math.pi, TWO_PI, ALU.is_lt, ALU.mult, [K3], [K4])
    k.tt('dve', a(r), a(r), a(m), ALU.add, [K3, K4], [K3])
    k.ts('dve', a(kf), a(r), PI_LO, -PI_LO, ALU.min, ALU.max, [K3], [K1])
    k.act(s_out, a(kf), AF.Sin, [K1], [skey])
    k.ts('dve', a(r), a(r), math.pi / 2, None, ALU.add, None, [K3], [K3])
    k.ts('dve', a(m), a(r), math.pi, -TWO_PI, ALU.is_gt, ALU.mult, [K3], [K4])
    k.tt('dve', a(r), a(r), a(m), ALU.add, [K3, K4], [K3])
    k.ts('dve', a(kf), a(r), PI_LO, -PI_LO, ALU.min, ALU.max, [K3], [K1])
    k.act(c_out, a(kf), AF.Sin, [K1], [ckey])


def gen_S5(L, k):
    NT = L // 128
    NS = 1024
    uT = k.din("uT", [256, L])
    u = k.din("u", [L, 256])
    lam_re = k.din("lam_re", [NS])
    lam_im = k.din("lam_im", [NS])
    lstep = k.din("lstep", [NS])
    Bre = k.din("Bre", [2, 128, 512])
    Bim = k.din("Bim", [2, 128, 512])
    Cre = k.din("Cre", [8, 128, 32])
    Cim = k.din("Cim", [8, 128, 32])
    dsk = k.din("dsk", [256])
    triu_d = k.din("triu", [128, 128])
    iop_d = k.din("iota_p", [128, 1])
    iof_d = k.din("iota_f", [128, 128])
    y = k.dout("y", [L, 256])

    k.push_scope([("triu_s", [128, 128], F32), ("dbc", [128, 256], F32), ("BBr", [128, 2, 512], mybir.dt.float32r), ("BBi", [128, 2, 512], mybir.dt.float32r),
                  ("Pr", [128, NS], F32), ("Pi", [128, NS], F32), ("Qr", [128, 8, 128], F32), ("Qi", [128, 8, 128], F32),
                  ("L128r", [128, 8], F32), ("L128i", [128, 8], F32), ("Cr", [128, 8, 32], F32), ("nCi", [128, 8, 32], F32),
                  ("car_r", [128, 8], F32), ("car_i", [128, 8], F32), ("ntriu", [128, 128], mybir.dt.float32r), ("nCr", [128, 8, 32], mybir.dt.float32r), ("triur", [128, 128], mybir.dt.float32r), ("Crr", [128, 8, 32], mybir.dt.float32r), ("nCir", [128, 8, 32], mybir.dt.float32r)])
    triu = k.sb("triu_s", [128, 128])
    k.dma('sp', triu[:], triu_d, w=['triu'])
    iop = k.sb("iop", [128, 1])
    k.dma('sp', iop[:], iop_d, w=['iop'])
    negp = k.sb("negp", [128, 1])
    k.ts('dve', negp[:], iop[:], -1.0, None, ALU.mult, None, ['iop'], ['negp'])
    iof = k.sb("iof", [128, 128])
    k.dma('sp', iof[:], iof_d, w=['iof'])
    dbc = k.bcast_row("dbc", dsk, 256)
    R = [128, NS]
    lr = k.bcast_row("lr", lam_re, NS)
    li = k.bcast_row("li", lam_im, NS)
    dl = k.bcast_row("dl", lstep, NS)
    k.ts('dve', lr[:], lr[:], -1e-4, None, ALU.min, None, ['lr'], ['lr'])
    k.act(dl[:], dl[:], AF.Exp, ['dl'], ['dl'])
    a_ = k.sb("a_", R)
    th = k.sb("th", R)
    k.tt('dve', a_[:], lr[:], dl[:], ALU.mult, ['lr', 'dl'], ['a_'])
    k.tt('dve', th[:], li[:], dl[:], ALU.mult, ['li', 'dl'], ['th'])
    sn = k.sb("sn", R)
    cs = k.sb("cs", R)
    range_sincos(k, th[:], 'th', R, sn[:], cs[:], 'sn', 'cs', 'rr_')
    ea = k.sb("ea", R)
    k.act(ea[:], a_[:], AF.Exp, ['a_'], ['ea'])
    nr = k.sb("nr", R)
    ni = k.sb("ni", R)
    k.tt('dve', nr[:], ea[:], cs[:], ALU.mult, ['ea', 'cs'], ['nr'])
    k.ts('dve', nr[:], nr[:], -1.0, None, ALU.add, None, ['nr'], ['nr'])
    k.tt('dve', ni[:], ea[:], sn[:], ALU.mult, ['ea', 'sn'], ['ni'])
    den = k.sb("den", R)
    t0 = k.sb("t0", R)
    k.tt('dve', den[:], lr[:], lr[:], ALU.mult, ['lr'], ['den'])
    k.tt('dve', t0[:], li[:], li[:], ALU.mult, ['li'], ['t0'])
    k.tt('dve', den[:], den[:], t0[:], ALU.add, ['den', 't0'], ['den'])
    k.recip(den[:], den[:], ['den'], ['den'])
    gr = k.sb("gr", R)
    gi = k.sb("gi", R)
    k.tt('dve', gr[:], nr[:], lr[:], ALU.mult, ['nr', 'lr'], ['gr'])
    k.tt('dve', t0[:], ni[:], li[:], ALU.mult, ['ni', 'li'], ['t0'])
    k.tt('dve', gr[:], gr[:], t0[:], ALU.add, ['gr', 't0'], ['gr'])
    k.tt('dve', gr[:], gr[:], den[:], ALU.mult, ['gr', 'den'], ['gr'])
    k.tt('dve', gi[:], ni[:], lr[:], ALU.mult, ['ni', 'lr'], ['gi'])
    k.tt('dve', t0[:], nr[:], li[:], ALU.mult, ['nr', 'li'], ['t0'])
    k.tt('dve', gi[:], gi[:], t0[:], ALU.subtract, ['gi', 't0'], ['gi'])
    k.tt('dve', gi[:], gi[:], den[:], ALU.mult, ['gi', 'den'], ['gi'])
    Br = k.sb("Br", [128, 2, 512])
    Bi = k.sb("Bi", [128, 2, 512])
    BBr = k.sb("BBr", [128, 2, 512])
    BBi = k.sb("BBi", [128, 2, 512])
    for hc in range(2):
        k.dma('sp', Br[:, hc, :], Bre[hc], w=[f'Br{hc}'])
        k.dma('sp', Bi[:, hc, :], Bim[hc], w=[f'Bi{hc}'])
    grv = gr[:].rearrange("p (h n) -> p h n", h=2)
    giv = gi[:].rearrange("p (h n) -> p h n", h=2)
    t0v = t0[:].rearrange("p (h n) -> p h n", h=2)
    BK = ['Br0', 'Br1', 'Bi0', 'Bi1']
    k.tt('dve', BBr[:], grv, Br[:], ALU.mult, ['gr'] + BK, ['BBr'])
    k.tt('dve', t0v, giv, Bi[:], ALU.mult, ['gi'] + BK, ['t0'])
    k.tt('dve', BBr[:], BBr[:].bitcast(F32), t0v, ALU.subtract, ['BBr', 't0'], ['BBr'])
    k.tt('dve', BBi[:], grv, Bi[:], ALU.mult, ['gr'] + BK, ['BBi'])
    k.tt('dve', t0v, giv, Br[:], ALU.mult, ['gi'] + BK, ['t0'])
    k.tt('dve', BBi[:], BBi[:].bitcast(F32), t0v, ALU.add, ['BBi', 't0'], ['BBi'])
    ang = k.sb("ang", R)
    k.ts('dve', ang[:], th[:], iop[:, 0:1], None, ALU.mult, None, ['th', 'iop'], ['ang'])
    Pr = k.sb("Pr", R)
    Pi = k.sb("Pi", R)
    range_sincos(k, ang[:], 'ang', R, sn[:], cs[:], 'sn', 'cs', 'rr_')
    k.act(ea[:], a_[:], AF.Exp, ['a_', 'negp'], ['ea'], scale=negp[:, 0:1])
    k.tt('dve', Pr[:], ea[:], cs[:], ALU.mult, ['ea', 'cs'], ['Pr'])
    k.stt(Pi[:], ea[:], -1.0, sn[:], ALU.mult, ALU.mult, ['ea', 'sn'], ['Pi'])
    Cs = [128, 8]
    lrc = k.sb("lrc", Cs)
    lic = k.sb("lic", Cs)
    dlc = k.sb("dlc", Cs)
    cv = lambda d: d.rearrange("(blk p) -> p blk", p=128)
    k.dma('sp', lrc[:], cv(lam_re), w=['lrc'], allow_slow_non_contiguous=True)
    k.dma('sp', lic[:], cv(lam_im), w=['lic'], allow_slow_non_contiguous=True)
    k.dma('sp', dlc[:], cv(lstep), w=['dlc'], allow_slow_non_contiguous=True)
    k.ts('dve', lrc[:], lrc[:], -1e-4, None, ALU.min, None, ['lrc'], ['lrc'])
    k.act(dlc[:], dlc[:], AF.Exp, ['dlc'], ['dlc'])
    ac = k.sb("ac", Cs)
    thc = k.sb("thc", Cs)
    k.tt('dve', ac[:], lrc[:], dlc[:], ALU.mult, ['lrc', 'dlc'], ['ac'])
    k.tt('dve', thc[:], lic[:], dlc[:], ALU.mult, ['lic', 'dlc'], ['thc'])
    Qr = k.sb("Qr", [128, 8, 128])
    Qi = k.sb("Qi", [128, 8, 128])
    angv = ang[:].rearrange("p (b t) -> p b t", b=8)
    eav = ea[:].rearrange("p (b t) -> p b t", b=8)
    for blk in range(8):
        k.ts('dve', angv[:, blk, :], iof[:], thc[:, blk:blk + 1], None, ALU.mult, None, ['iof', 'thc'], ['ang'])
    range_sincos(k, ang[:], 'ang', R, sn[:], cs[:], 'sn', 'cs', 'rr_')
    for blk in range(8):
        k.act(eav[:, blk, :], iof[:], AF.Exp, ['iof', 'ac'], ['ea'], scale=ac[:, blk:blk + 1])
    k.tt('dve', Qr[:].rearrange("p b t -> p (b t)"), ea[:], cs[:], ALU.mult, ['ea', 'cs'], ['Qr'])
    k.tt('dve', Qi[:].rearrange("p b t -> p (b t)"), ea[:], sn[:], ALU.mult, ['ea', 'sn'], ['Qi'])
    a128 = k.sb("a128", Cs)
    s128 = k.sb("s128", Cs)
    c128 = k.sb("c128", Cs)
    L128r = k.sb("L128r", Cs)
    L128i = k.sb("L128i", Cs)
    k.ts('dve', a128[:], thc[:], 128.0, None, ALU.mult, None, ['thc'], ['a128'])
    range_sincos(k, a128[:], 'a128', Cs, s128[:], c128[:], 's128', 'c128', 'rc_')
    k.act(a128[:], ac[:], AF.Exp, ['ac', 's128', 'c128'], ['a128'], scale=128.0)
    k.tt('dve', L128r[:], a128[:], c128[:], ALU.mult, ['a128', 'c128'], ['L128r'])
    k.tt('dve', L128i[:], a128[:], s128[:], ALU.mult, ['a128', 's128'], ['L128i'])
    Cr = k.sb("Cr", [128, 8, 32])
    nCi = k.sb("nCi", [128, 8, 32])
    k.dma('sp', Cr[:], Cre.rearrange("b p c -> p b c"), w=['Cr'])
    k.dma('sp', nCi[:], Cim.rearrange("b p c -> p b c"), w=['nCi'])
    k.ts('dve', nCi[:], nCi[:], -1.0, None, ALU.mult, None, ['nCi'], ['nCi'])
    car_r = k.sb("car_r", Cs)
    car_i = k.sb("car_i", Cs)
    k.memset('dve', car_r[:], 0.0, ['car_r0', 'car_r1'])
    k.memset('dve', car_i[:], 0.0, ['car_i0', 'car_i1'])
    ntriu = k.sb("ntriu", [128, 128])
    k.ts('dve', ntriu[:], triu[:], -1.0, None, ALU.mult, None, ['triu'], ['ntriu'])
    nCr = k.sb("nCr", [128, 8, 32])
    k.ts('dve', nCr[:], Cr[:], -1.0, None, ALU.mult, None, ['Cr'], ['nCr'])
    triur = k.sb("triur", [128, 128])
    k.cp('dve', triur[:], triu[:], ['triu'], ['triur'])
    Crr = k.sb("Crr", [128, 8, 32])
    k.cp('dve', Crr[:], Cr[:], ['Cr'], ['Crr'])
    nCir = k.sb("nCir", [128, 8, 32])
    k.cp('dve', nCir[:], nCi[:], ['nCi'], ['nCir'])
    k.pop_scope()
    if hasattr(k, 'rr_cache'):
        del k.rr_cache
    def ring(nm, shape, n, dt=F32):
        return [k.sb(f"{nm}{j}", shape, dt) for j in range(n)]
    FR_ = mybir.dt.float32r
    uTt = ring("uTt", [128, 128], 3)
    uTr = ring("uTr", [128, 128], 3, FR_)
    ut = ring("ut", [128, 128], 5)
    yo = ring("yo", [128, 128], 9)
    m1, m2, m3, m4 = ring("m1_", [128, 512], 3, FR_), ring("m2_", [128, 512], 3, FR_), ring("m3_", [128, 512], 3, FR_), ring("m4_", [128, 512], 3, FR_)
    Xtr, Xti = ring("Xtr", [128, 512], 3), ring("Xti", [128, 512], 3)
    Gr, Gi = ring("Gr", [128, 4, 128], 3), ring("Gi", [128, 4, 128], 3)
    n1, n2, n3, n4 = ring("n1_", [128, 512], 3, FR_), ring("n2_", [128, 512], 3, FR_), ring("n3_", [128, 512], 3, FR_), ring("n4_", [128, 512], 3, FR_)
    Hr, Hi = ring("Hr", [128, 4, 128], 3), ring("Hi", [128, 4, 128], 3)
    cc1 = [k.sb(f"cc1_{h}", [128, 4]) for h in range(2)]
    cc2 = [k.sb(f"cc2_{h}", [128, 4]) for h in range(2)]
    psXr = k.ps("psXr", [128, 512])
    psXi = k.ps("psXi", [128, 512])
    psGr = k.ps("psGr", [128, 512])
    psGi = k.ps("psGi", [128, 512])
    psY = k.ps("psY", [128, 512])
    fl = lambda t: t[:].rearrange("p b t -> p (b t)")

    def item(j):
        i, hc = divmod(j, 2)
        rows = slice(i * 128, (i + 1) * 128)
        cs_ = slice(hc * 512, (hc + 1) * 512)
        bs = slice(hc * 4, (hc + 1) * 4)
        def T(lst, nm):
            q = j % len(lst)
            return lst[q], f'{nm}{q}'
        uT_, kuT = T(uTt, 'uTt'); uR_, kuR = T(uTr, 'uTr'); ut_, kut = T(ut, 'ut'); yo_, kyo = T(yo, 'yo')
        m1_, km1 = T(m1, 'm1'); m2_, km2 = T(m2, 'm2'); m3_, km3 = T(m3, 'm3'); m4_, km4 = T(m4, 'm4')
        Xr_, kXr = T(Xtr, 'Xtr'); Xi_, kXi = T(Xti, 'Xti'); Gr_, kGr = T(Gr, 'Gr'); Gi_, kGi = T(Gi, 'Gi')
        n1_, kn1 = T(n1, 'n1'); n2_, kn2 = T(n2, 'n2'); n3_, kn3 = T(n3, 'n3'); n4_, kn4 = T(n4, 'n4')
        Hr_, kHr = T(Hr, 'Hr'); Hi_, kHi = T(Hi, 'Hi')
        k.dma('sp', uT_[:], uT[hc * 128:(hc + 1) * 128, rows], w=[kuT])
        k.dma('sp', ut_[:], u[rows, hc * 128:(hc + 1) * 128], w=[kut])
        yield
        k.cp('act', uR_[:], uT_[:], [kuT], [kuR])
        yield
        k.mm(psXr[:], uR_[:], BBr[:, hc, :], True, True, [kuR, 'BBr'], ['psXr'])
        k.mm(psXi[:], uR_[:], BBi[:, hc, :], True, True, [kuR, 'BBi'], ['psXi'])
        yield
        k.tt('dve', m1_[:], psXr[:], Pr[:, cs_], ALU.mult, ['psXr', 'Pr'], [km1])
        k.tt('dve', m3_[:], psXr[:], Pi[:, cs_], ALU.mult, ['psXr', 'Pi'], [km3])
        k.tt('dve', m2_[:], psXi[:], Pi[:, cs_], ALU.mult, ['psXi', 'Pi'], [km2])
        k.tt('dve', m4_[:], psXi[:], Pr[:, cs_], ALU.mult, ['psXi', 'Pr'], [km4])
        yield
        k.tt('pool', yo_[:], ut_[:], dbc[:, hc * 128:(hc + 1) * 128], ALU.mult, [kut, 'dbc'], [kyo])
        yield
        for nb in range(4):
            ns = slice(nb * 128, (nb + 1) * 128)
            k.mm(psGr[:, ns], m1_[:, ns], triur[:], True, False, [km1, 'triur'], ['psGr'])
            k.mm(psGr[:, ns], m2_[:, ns], ntriu[:], False, True, [km2, 'ntriu'], ['psGr'])
            k.mm(psGi[:, ns], m3_[:, ns], triur[:], True, False, [km3, 'triur'], ['psGi'])
            k.mm(psGi[:, ns], m4_[:, ns], triur[:], False, True, [km4, 'triur'], ['psGi'])
        yield
        k.tt('dve', Gr_[:], psGr[:].rearrange("p (b t) -> p b t", b=4),
             car_r[:, bs].unsqueeze(2).broadcast_to([128, 4, 128]), ALU.add, ['psGr', f'car_r{hc}'], [kGr])
        k.tt('dve', Gi_[:], psGi[:].rearrange("p (b t) -> p b t", b=4),
             car_i[:, bs].unsqueeze(2).broadcast_to([128, 4, 128]), ALU.add, ['psGi', f'car_i{hc}'], [kGi])
        gr127 = Gr_[:, :, 127]
        gi127 = Gi_[:, :, 127]
        CK = [f'cc1{hc}', f'cc2{hc}']
        k.tt('dve', cc1[hc][:], L128r[:, bs], gr127, ALU.mult, ['L128r', kGr], [CK[0]])
        k.tt('dve', cc2[hc][:], L128i[:, bs], gi127, ALU.mult, ['L128i', kGi], [CK[1]])
        k.tt('dve', car_r[:, bs], cc1[hc][:], cc2[hc][:], ALU.subtract, CK, [f'car_r{hc}'])
        k.tt('dve', cc1[hc][:], L128r[:, bs], gi127, ALU.mult, ['L128r', kGi], [CK[0]])
        k.tt('dve', cc2[hc][:], L128i[:, bs], gr127, ALU.mult, ['L128i', kGr], [CK[1]])
        k.tt('dve', car_i[:, bs], cc1[hc][:], cc2[hc][:], ALU.add, CK, [f'car_i{hc}'])
        yield
        qr = Qr[:, bs, :].rearrange("p b t -> p (b t)")
        qi = Qi[:, bs, :].rearrange("p b t -> p (b t)")
        k.tt('dve', n1_[:], fl(Gr_), qr, ALU.mult, [kGr, 'Qr'], [kn1])
        k.tt('dve', n2_[:], fl(Gi_), qi, ALU.mult, [kGi, 'Qi'], [kn2])
        k.tt('dve', n3_[:], fl(Gi_), qr, ALU.mult, [kGi, 'Qr'], [kn3])
        k.tt('dve', n4_[:], fl(Gr_), qi, ALU.mult, [kGr, 'Qi'], [kn4])
        yield
        for nb in range(4):
            blk = hc * 4 + nb
            ns = slice(nb * 128, (nb + 1) * 128)
            yo_s = psY[:, blk * 32:(blk + 1) * 32]
            k.mm(yo_s, n1_[:, ns], Crr[:, blk, :], True, False, [kn1, 'Crr'], ['psY'])
            k.mm(yo_s, n2_[:, ns], nCr[:, blk, :], False, False, [kn2, 'nCr'], ['psY'])
            k.mm(yo_s, n3_[:, ns], nCir[:, blk, :], False, False, [kn3, 'nCir'], ['psY'])
            k.mm(yo_s, n4_[:, ns], nCir[:, blk, :], False, True, [kn4, 'nCir'], ['psY'])
        yield
        k.tt('dve', yo_[:], yo_[:], psY[:, hc * 128:(hc + 1) * 128], ALU.add, [kyo, 'psY'], [kyo])
        yield
        k.dma('pool', y[rows, hc * 128:(hc + 1) * 128], yo_[:], r=[kyo], final=True)

    yield from pipeline_gen(item, 2 * NT)


def build_S5(L, k=None):
    k = k or K()
    for _ in gen_S5(L, k):
        pass
    return k.finish()


def s5_host_inputs(s, proj_u, prm):
    gs = slice(16 * s, 16 * s + 16)
    cs = slice(256 * s, 256 * s + 256)
    uc = np.ascontiguousarray(proj_u[:, cs])
    Bre = np.zeros((2, 128, 512), np.float32)
    Bim = np.zeros((2, 128, 512), np.float32)
    Cre = np.zeros((8, 128, 32), np.float32)
    Cim = np.zeros((8, 128, 32), np.float32)
    b_re, b_im = prm['s5_b_re'][gs], prm['s5_b_im'][gs]
    c_re, c_im = prm['s5_c_re'][gs], prm['s5_c_im'][gs]
    for g in range(16):
        hc, gl = g // 8, g % 8
        Bre[hc, gl * 16:(gl + 1) * 16, gl * 64:(gl + 1) * 64] = b_re[g].T
        Bim[hc, gl * 16:(gl + 1) * 16, gl * 64:(gl + 1) * 64] = b_im[g].T
        blk, g2 = g // 2, g % 2
        Cre[blk, g2 * 64:(g2 + 1) * 64, g2 * 16:(g2 + 1) * 16] = c_re[g].T
        Cim[blk, g2 * 64:(g2 + 1) * 64, g2 * 16:(g2 + 1) * 16] = c_im[g].T
    return dict(uT=np.ascontiguousarray(uc.T), u=uc,
                lam_re=np.ascontiguousarray(prm['s5_lambda_re'][gs].reshape(-1)),
                lam_im=np.ascontiguousarray(prm['s5_lambda_im'][gs].reshape(-1)),
                lstep=np.ascontiguousarray(np.repeat(prm['s5_log_step'][gs], 64)),
                Bre=Bre, Bim=Bim, Cre=Cre, Cim=Cim, dsk=np.ascontiguousarray(prm['s5_d'][cs]),
                triu=np.triu(np.ones((128, 128), np.float32)),
                iota_p=np.arange(128, dtype=np.float32).reshape(128, 1),
                iota_f=np.tile(np.arange(128, dtype=np.float32)[None], (128, 1)))


GELU_C = 1.5957691216057308


def gen_LRU(L, k):
    TT = 512
    NCH = L // TT
    xbT = k.din("xbT", [256, L])
    gateT = k.din("gateT", [256, L])
    cw_d = k.din("cw", [128, 2, 4])
    cb_d = k.din("cb", [128, 2])
    Wa_d = k.din("Wa", [2, 128, 128])
    Wx_d = k.din("Wx", [2, 128, 128])
    ba_d = k.din("ba", [128, 2])
    bx_d = k.din("bx", [128, 2])
    lam_d = k.din("lam", [128, 2])
    odT = k.dout("odT", [256, L])
    cw = k.sb("cw_s", [128, 2, 4])
    cb = k.sb("cb_s", [128, 2])
    Wa = k.sb("Wa_s", [128, 2, 128])
    Wx = k.sb("Wx_s", [128, 2, 128])
    ba = k.sb("ba_s", [128, 2])
    bx = k.sb("bx_s", [128, 2])
    c8 = k.sb("c8", [128, 2])
    k.dma('sp', cw[:], cw_d, w=['cw'])
    k.dma('sp', cb[:], cb_d, w=['cb'])
    k.dma('sp', Wa[:], Wa_d.rearrange("b p n -> p b n"), w=['Wa'])
    k.dma('sp', Wx[:], Wx_d.rearrange("b p n -> p b n"), w=['Wx'])
    k.dma('sp', ba[:], ba_d, w=['ba'])
    k.dma('sp', bx[:], bx_d, w=['bx'])
    k.dma('sp', c8[:], lam_d, w=['c8'])
    k.act(c8[:], c8[:], AF.Exp, ['c8'], ['c8'], scale=-1.0)
    k.act(c8[:], c8[:], AF.Ln, ['c8'], ['c8'], bias=1.0)
    k.ts('dve', c8[:], c8[:], -8.0, None, ALU.mult, None, ['c8'], ['c8'])
    hlast = k.sb("hlast", [128, 2])
    k.memset('dve', hlast[:], 0.0, ['hlast0', 'hlast1'])
    xh = [k.sb(f"xh{i}", [128, TT + 3]) for i in range(2)]
    gt = [k.sb(f"gt{i}", [128, TT]) for i in range(2)]
    xc = k.sb("xc", [128, TT])
    r = k.sb("r", [128, TT])
    ig = k.sb("ig", [128, TT])
    a = k.sb("a", [128, TT])
    a2 = k.sb("a2", [128, TT])
    bt = k.sb("bt", [128, TT])
    h = k.sb("h", [128, TT])
    g2 = k.sb("g2", [128, TT])
    ge = k.sb("ge", [128, TT])
    ot = [k.sb(f"ot{i}", [128, TT]) for i in range(2)]
    psR = k.ps("psR", [128, TT])
    psI = k.ps("psI", [128, TT])
    n = 0
    for c in range(NCH):
        for pb in range(2):
            b = n % 2
            n += 1
            prow = slice(pb * 128, (pb + 1) * 128)
            if c == 0:
                k.memset('pool', xh[b][:, 0:3], 0.0, [f'xh{b}h'])
                k.dma('sp', xh[b][:, 3:TT + 3], xbT[prow, 0:TT], w=[f'xh{b}'])
            else:
                k.dma('sp', xh[b][:, 0:TT + 3], xbT[prow, c * TT - 3:(c + 1) * TT], w=[f'xh{b}', f'xh{b}h'])
            k.dma('sp', gt[b][:], gateT[prow, c * TT:(c + 1) * TT], w=[f'gt{b}'])
            xk = [f'xh{b}', f'xh{b}h']
            k.ts('dve', xc[:], xh[b][:, 3:TT + 3], cw[:, pb, 3:4], cb[:, pb:pb + 1], ALU.mult, ALU.add, xk + ['cw', 'cb'], ['xc'])
            for j in (2, 1, 0):
                k.stt(xc[:], xh[b][:, j:j + TT], cw[:, pb, j:j + 1], xc[:], ALU.mult, ALU.add, xk + ['cw', 'xc'], ['xc'])
            k.mm(psR[:], Wa[:, pb, :], xc[:], True, True, ['Wa', 'xc'], ['psR'])
            k.mm(psI[:], Wx[:, pb, :], xc[:], True, True, ['Wx', 'xc'], ['psI'])
            k.act(r[:], psR[:], AF.Sigmoid, ['psR', 'ba'], ['r'], bias=ba[:, pb:pb + 1])
            k.act(ig[:], psI[:], AF.Sigmoid, ['psI', 'bx'], ['ig'], bias=bx[:, pb:pb + 1])
            k.act(a[:], r[:], AF.Exp, ['r', 'c8'], ['a'], scale=c8[:, pb:pb + 1])
            k.act(a2[:], a[:], AF.Square, ['a'], ['a2'])
            k.act(a2[:], a2[:], AF.Sqrt, ['a2'], ['a2'], scale=-1.0, bias=1.0)
            k.tt('pool', bt[:], ig[:], xc[:], ALU.mult, ['ig', 'xc'], ['bt'])
            k.tt('pool', bt[:], bt[:], a2[:], ALU.mult, ['bt', 'a2'], ['bt'])
            k.P.op('dve', lambda e, pb=pb: e.tensor_tensor_scan(out=h[:], data0=a[:], data1=bt[:], initial=hlast[:, pb:pb + 1],
                                                                op0=ALU.mult, op1=ALU.add),
                   reads=['a', 'bt', f'hlast{pb}'], writes=['h'])
            k.cp('dve', hlast[:, pb:pb + 1], h[:, TT - 1:TT], ['h'], [f'hlast{pb}'])
            k.act(g2[:], gt[b][:], AF.Square, [f'gt{b}'], ['g2'])
            k.act(g2[:], g2[:], AF.Copy, ['g2'], ['g2'], scale=0.044715, bias=1.0)
            k.tt('pool', g2[:], g2[:], gt[b][:], ALU.mult, ['g2', f'gt{b}'], ['g2'])
            k.act(g2[:], g2[:], AF.Sigmoid, ['g2'], ['g2'], scale=GELU_C)
            k.tt('pool', ge[:], g2[:], gt[b][:], ALU.mult, ['g2', f'gt{b}'], ['ge'])
            k.tt('dve', ot[b][:], h[:], ge[:], ALU.mult, ['h', 'ge'], [f'ot{b}'])
            k.dma('pool', odT[prow, c * TT:(c + 1) * TT], ot[b][:], r=[f'ot{b}'], final=True)
            yield


def build_LRU(L, k=None):
    k = k or K()
    for _ in gen_LRU(L, k):
        pass
    return k.finish()


def lru_host_inputs(s, xb, gate, prm):
    cs = slice(256 * s, 256 * s + 256)
    col = lambda v: np.ascontiguousarray(v[cs].reshape(2, 128).T)
    Wa = np.zeros((2, 128, 128), np.float32)
    Wx = np.zeros((2, 128, 128), np.float32)
    for pb in range(2):
        for bl in range(2):
            blk = 4 * s + 2 * pb + bl
            Wa[pb, bl * 64:(bl + 1) * 64, bl * 64:(bl + 1) * 64] = prm['lru_w_a'][blk]
            Wx[pb, bl * 64:(bl + 1) * 64, bl * 64:(bl + 1) * 64] = prm['lru_w_x'][blk]
    cw = np.ascontiguousarray(prm['lru_conv_w'][:, cs].reshape(4, 2, 128).transpose(2, 1, 0))
    return dict(xbT=np.ascontiguousarray(xb[:, cs].T), gateT=np.ascontiguousarray(gate[:, cs].T), cw=cw,
                cb=col(prm['lru_conv_b']), Wa=Wa, Wx=Wx, ba=col(prm['lru_b_a']), bx=col(prm['lru_b_x']),
                lam=col(prm['lru_lambda']))


GN_EPS = 64e-5
NLEV = 5


def build_RWKV(L, k=None, NH=4, fr=False, CH=64):
    k = k or K()
    NT = L // 128
    W = NH * 64
    NG = NH // 4
    FR = mybir.dt.float32r if fr else F32
    rd = (lambda ap: ap.bitcast(F32)) if fr else (lambda ap: ap)
    NCK = 128 // CH
    nlev = 5 if CH == 64 else 6
    frc = fr and CH == 128
    FRC = mybir.dt.float32r if frc else F32
    rdc = (lambda ap: ap.bitcast(F32)) if frc else (lambda ap: ap)
    lhc = (lambda ap: ap) if frc else rd
    prkv = [k.din(nm, [L, W]) for nm in ("pr", "pk", "pv")]
    mu1 = k.din("mu1", [3 * W])
    pls = [k.din("plw", [64, L]), k.din("pla", [64, L]), k.din("plg", [128, L])]
    mul = k.din("mul", [128, 3])
    w2 = k.din("w2", [64, W])
    a2 = k.din("a2", [64, W])
    g2 = k.din("g2", [128, W])
    vecs = k.din("vecs", [7, W])
    ident_d = k.din("ident", [128, 128])
    triw_d = k.din("triw", [3, 128, 128])
    mask5_d = k.din("mask5", [128, 640])
    rowm_d = k.din("rowm", [128, 2])
    oc = k.dout("oc", [L, W])

    k.consts(ident_d)
    triw = k.sb("triw_s", [128, 3, 128])
    k.dma('sp', triw[:], triw_d.rearrange("a p n -> p a n"), w=['triw'])
    mask5 = k.sb("mask5_s", [128, 640])
    k.dma('sp', mask5[:], mask5_d, w=['mask5'])
    rowm = k.sb("rowm_s", [128, 2])
    k.dma('sp', rowm[:], rowm_d, w=['rowm'])
    mu1bc = k.bcast_row("mu1bc", mu1, 3 * W)
    vb = [k.bcast_row(f"vb{i}", vecs[i], W) for i in range(7)]
    w0bc, a0bc, kkbc, kabc, rkbc, lngbc, lnbbc = vb
    VK = [f"vb{i}" for i in range(7)]
    muls = k.sb("muls", [128, 3])
    k.dma('sp', muls[:], mul, w=['muls'])
    w2s = k.sb("w2s", [64, W])
    a2s = k.sb("a2s", [64, W])
    k.dma('sp', w2s[:], w2, w=['w2s'])
    k.dma('sp', a2s[:], a2, w=['a2s'])
    g2s = k.sb("g2s", [128, W])
    k.dma('sp', g2s[:], g2, w=['g2s'])
    ST = [k.sb(f"ST{i}", [64, 64], FRC) for i in range(NH)]
    zt = k.sb("zt", [128, W])
    k.memset('dve', zt[:], 0.0, ['zt'])
    for i in range(NH):
        k.cp('dve', ST[i][:], zt[0:64, 0:64], ['zt'], [f'ST{i}'])
    P1s = k.sb("P1s", [128, W], FRC)
    Us = k.sb("Us", [128, W], FRC)
    k.cp('dve', P1s[:], zt[:], ['zt'], ['P1s'])
    k.cp('dve', Us[:], zt[:], ['zt'], ['Us'])

    pt = [k.sb(f"pt{i}", [128, 3 * W]) for i in range(2)]
    pp = [k.sb(f"pp{i}", [128, 3 * W]) for i in range(2)]
    lt = [k.sb(f"lt{i}", [128, 3, 128]) for i in range(2)]
    lp = [k.sb(f"lp{i}", [128, 3, 128]) for i in range(2)]
    for i_ in range(2):
        k.memset('pool', lt[i_][:], 0.0, [f'lt{i_}0', f'lt{i_}1', f'lt{i_}2'])
        k.memset('pool', lp[i_][:], 0.0, [f'lp{i_}0', f'lp{i_}1', f'lp{i_}2', f'lp{i_}z'])
    pm = k.sb("pm", [128, 3 * W])
    vr = k.sb("vr", [128, W], FR)
    lm = k.sb("lm", [128, 3, 128])
    sw = k.sb("sw", [128, W])
    av = k.sb("av", [128, W])
    gv = k.sb("gv", [128, W])
    kkr = k.sb("kkr", [128, W])
    sq = k.sb("sq", [128, W])
    s4 = k.sb("s4", [128, NH])
    rn = k.sb("rn", [128, NH])
    nkk = k.sb("nkk", [128, W])
    kmod = k.sb("kmod", [128, W])
    kka = k.sb("kka", [128, W])
    tmp = k.sb("tmp", [128, W])
    bon = k.sb("bon", [128, NH])
    E1 = k.sb("E1", [128, W])
    E2 = k.sb("E2", [128, W])
    E3 = k.sb("E3", [128, W])
    E4 = k.sb("E4", [128, W])
    E1T = k.sb("E1T", [64, NH, 128])
    At = k.sb("At", [128, W])
    Bs = k.sb("Bs", [128, W])
    Ks = k.sb("Ks", [128, W])
    Rt = k.sb("Rt", [128, W])
    Bfm = [k.sb(f"Bfm{c}", [128, W]) for c in range(2)]
    Kfm = [k.sb(f"Kfm{c}", [128, W]) for c in range(2)]
    FT = [k.sb(f"FT{h}", [64, 4, 128], FR) for h in range(NH)]
    A5 = [k.sb(f"A5_{h}", [128, 640], FR) for h in range(NH)]
    NL = [k.sb(f"NL_{h}", [128, 256], FR) for h in range(NH)]
    PQ = [k.sb(f"PQ_{h}", [128, 256], FR) for h in range(NH)]
    W1 = k.sb("W1", [128, W], FR)
    U1 = k.sb("U1", [128, W])
    ysb = k.sb("ysb", [128, W])
    yc = k.sb("yc", [128, W])
    m4 = k.sb("m4", [128, NH])
    r4 = k.sb("r4", [128, NH])
    ot = [k.sb(f"ot{i}", [128, W]) for i in range(2)]
    B = [k.ps(f"psB{i}", [128, 512]) for i in range(8)]
    bk = lambda i: f'psB{i}'
    v3 = lambda t: t.rearrange("p (h j) -> p h j", h=NH)
    bc4 = lambda t: t.unsqueeze(2).broadcast_to([128, NH, 64])

    for i in range(NT):
        b = i % 2
        rows = slice(i * 128, (i + 1) * 128)
        PK, PPK, LTK, LPK = [], [], [], []
        for q in range(3):
            cq = slice(q * W, (q + 1) * W)
            k.dma('sp', pt[b][:, cq], prkv[q][rows, :], w=[f'pt{b}{q}'])
            PK.append(f'pt{b}{q}')
            if i == 0:
                k.dma('sp', pp[b][1:128, cq], prkv[q][0:127, :], w=[f'pp{b}{q}'])
            else:
                k.dma('sp', pp[b][:, cq], prkv[q][i * 128 - 1:i * 128 + 127, :], w=[f'pp{b}{q}'])
            PPK.append(f'pp{b}{q}')
            nr = pls[q].shape[0]
            k.dma('sp', lt[b][0:nr, q, :], pls[q][:, rows], w=[f'lt{b}{q}'])
            LTK.append(f'lt{b}{q}')
            if i == 0:
                k.dma('sp', lp[b][0:nr, q, 1:128], pls[q][:, 0:127], w=[f'lp{b}{q}'])
            else:
                k.dma('sp', lp[b][0:nr, q, :], pls[q][:, i * 128 - 1:i * 128 + 127], w=[f'lp{b}{q}'])
            LPK.append(f'lp{b}{q}')
        if i == 0:
            k.memset('pool', pp[b][0:1, :], 0.0, [f'pp{b}z'])
            k.memset('pool', lp[b][:, :, 0:1], 0.0, [f'lp{b}z'])
            PPK.append(f'pp{b}z')
            LPK.append(f'lp{b}z')
        k.tt('pool', pm[:], pp[b][:], pt[b][:], ALU.subtract, PPK + PK, ['pm'])
        k.tt('pool', pm[:], pm[:], mu1bc[:], ALU.mult, ['pm', 'mu1bc'], ['pm'])
        k.tt('pool', pm[:], pm[:], pt[b][:], ALU.add, ['pm'] + PK, ['pm'])
        r_, k_, v_ = pm[:, 0:W], pm[:, W:2 * W], pm[:, 2 * W:3 * W]
        k.cp('act', vr[:], v_, ['pm'], ['vr'])
        LK = LTK + LPK
        k.tt('dve', lm[:], lp[b][:], lt[b][:], ALU.subtract, LK, ['lm'])
        for blk in range(3):
            k.stt(lm[:, blk, :], lm[:, blk, :], muls[:, blk:blk + 1], lt[b][:, blk, :], ALU.mult, ALU.add,
                  ['lm', 'muls'] + LK, ['lm'])
        k.act(lm[0:64, 0, :], lm[0:64, 0, :], AF.Tanh, ['lm'], ['lm'])
        k.act(lm[:, 2, :], lm[:, 2, :], AF.Sigmoid, ['lm'], ['lm'])
        k.mm(B[0][:, 0:W], lm[0:64, 0, :], w2s[:], True, True, ['lm', 'w2s'], [bk(0)])
        k.mm(B[1][:, 0:W], lm[0:64, 1, :], a2s[:], True, True, ['lm', 'a2s'], [bk(1)])
        k.mm(B[2][:, 0:W], lm[:, 2, :], g2s[:], True, True, ['lm', 'g2s'], [bk(2)])
        k.tt('dve', sw[:], B[0][:, 0:W], w0bc[:], ALU.add, [bk(0), VK[0]], ['sw'])
        k.act(sw[:], sw[:], AF.Sigmoid, ['sw'], ['sw'])
        k.tt('dve', av[:], B[1][:, 0:W], a0bc[:], ALU.add, [bk(1), VK[1]], ['av'])
        k.act(av[:], av[:], AF.Sigmoid, ['av'], ['av'])
        k.cp('act', gv[:], B[2][:, 0:W], [bk(2)], ['gv'])
        k.tt('pool', kkr[:], k_, kkbc[:], ALU.mult, ['pm', VK[2]], ['kkr'])
        k.tt('pool', sq[:], kkr[:], kkr[:], ALU.mult, ['kkr'], ['sq'])
        k.P.op('dve', lambda e: e.tensor_reduce(out=s4[:], in_=v3(sq[:]), axis=AX.X, op=ALU.add), reads=['sq'], writes=['s4'])
        k.act(s4[:], s4[:], AF.Sqrt, ['s4'], ['s4'])
        k.ts('dve', s4[:], s4[:], 1e-12, None, ALU.max, None, ['s4'], ['s4'])
        k.recip(rn[:], s4[:], ['s4'], ['rn'])
        k.ts('dve', rn[:], rn[:], -1.0, None, ALU.mult, None, ['rn'], ['rn'])
        k.tt('dve', v3(nkk[:]), v3(kkr[:]), bc4(rn[:]), ALU.mult, ['kkr', 'rn'], ['nkk'])
        k.stt(tmp[:], av[:], -1.0, kabc[:], ALU.add, ALU.mult, ['av', VK[3]], ['tmp'])
        k.stt(kmod[:], tmp[:], 1.0, k_, ALU.add, ALU.mult, ['tmp', 'pm'], ['kmod'])
        k.stt(kka[:], nkk[:], -1.0, av[:], ALU.mult, ALU.mult, ['nkk', 'av'], ['kka'])
        k.tt('pool', tmp[:], r_, kmod[:], ALU.mult, ['pm', 'kmod', 'tmp'], ['tmp'])
        k.tt('pool', tmp[:], tmp[:], rkbc[:], ALU.mult, ['tmp', VK[4]], ['tmp'])
        k.P.op('dve', lambda e: e.tensor_reduce(out=bon[:], in_=v3(tmp[:]), axis=AX.X, op=ALU.add), reads=['tmp'], writes=['bon'])
        k.mm(B[3][:, 0:W], triw[:, 0, :], sw[:], True, True, ['triw', 'sw'], [bk(3)])
        k.mm(B[4][:, 0:W], triw[:, 1, :], sw[:], True, True, ['triw', 'sw'], [bk(4)])
        k.mm(B[5][:, 0:W], triw[:, 2, :], sw[:], True, True, ['triw', 'sw'], [bk(5)])
        for h in range(NH):
            k.mm(B[6 + h // 4][0:64, (h % 4) * 128:(h % 4 + 1) * 128], sw[:, h * 64:(h + 1) * 64], triw[:, 0, :], True, True,
                 ['sw', 'triw'], [bk(6 + h // 4)])
        k.act(E1[:], B[3][:, 0:W], AF.Exp, [bk(3)], ['E1'])
        k.act(E2[:], B[3][:, 0:W], AF.Exp, [bk(3)], ['E2'], scale=-1.0)
        k.act(E3[:], B[4][:, 0:W], AF.Exp, [bk(4)], ['E3'])
        k.act(E4[:], B[5][:, 0:W], AF.Exp, [bk(5)], ['E4'])
        for g in range(NG):
            k.act(E1T[:, 4 * g:4 * g + 4, :].rearrange("p a t -> p (a t)"), B[6 + g][0:64, :], AF.Exp, [bk(6 + g)], ['E1T'])
        k.tt('dve', At[:], nkk[:], E3[:], ALU.mult, ['nkk', 'E3'], ['At'])
        k.tt('pool', Bs[:], kka[:], E2[:], ALU.mult, ['kka', 'E2'], ['Bs'])
        k.tt('dve', Ks[:], kmod[:], E2[:], ALU.mult, ['kmod', 'E2'], ['Ks'])
        k.tt('pool', Rt[:], r_, E1[:], ALU.mult, ['pm', 'E1'], ['Rt'])
        for c in range(NCK):
            k.stt(Bfm[c][:], kka[:], rowm[:, c:c + 1], E4[:], ALU.mult, ALU.mult, ['kka', 'E4', 'rowm'], [f'Bfm{c}'])
            k.stt(Kfm[c][:], kmod[:], rowm[:, c:c + 1], E4[:], ALU.mult, ALU.mult, ['kmod', 'E4', 'rowm'], [f'Kfm{c}'])
        HS = list(range(NH))
        for h in HS:
            cs_ = slice(h * 64, (h + 1) * 64)
            for q, (src, key) in enumerate([(At, 'At'), (Bs, 'Bs'), (Ks, 'Ks'), (Rt, 'Rt')]):
                k.tr(B[h][0:64, q * 128:(q + 1) * 128], src[:, cs_], k.identf[:], [key], [bk(h)])
        for h in HS:
            k.cp('act' if h % 2 else 'dve', FT[h][:].rearrange("p a t -> p (a t)"), B[h][0:64, :], [bk(h)], [f'FT{h}'])
        for h in HS:
            AtT, BsT, KsT, RtT = (FT[h][:, q, :] for q in range(4))
            o = lambda j: B[h][:, j * 128:(j + 1) * 128]
            k.mm(o(0), BsT, AtT, True, True, [f'FT{h}'], [bk(h)])
            k.mm(o(1), AtT, BsT, True, True, [f'FT{h}'], [bk(h)])
            k.mm(o(2), KsT, AtT, True, True, [f'FT{h}'], [bk(h)])
        for h in HS:
            k.tt('dve', A5[h][:, 0:384], B[h][:, 0:384], mask5[:, 0:384], ALU.mult, [bk(h), 'mask5'], [f'A5_{h}'])
        for h in HS:
            AtT, BsT, KsT, RtT = (FT[h][:, q, :] for q in range(4))
            k.mm(B[h][:, 0:128], BsT, RtT, True, True, [f'FT{h}'], [bk(h)])
            k.mm(B[h][:, 128:256], KsT, RtT, True, True, [f'FT{h}'], [bk(h)])
        for h in HS:
            k.tt('dve', A5[h][:, 384:640], B[h][:, 0:256], mask5[:, 384:640], ALU.mult, [bk(h), 'mask5'], [f'A5b_{h}'])
            k.cp('act', NL[h][:], rd(A5[h][:, 0:256]), [f'A5_{h}'], [f'NL_{h}'])
            k.tt('pool' if not fr else 'dve', PQ[h][:].rearrange("p (a n) -> p a n", a=2), rd(A5[h][:, 0:256]).rearrange("p (a n) -> p a n", a=2),
                 k.identf[:].unsqueeze(1).broadcast_to([128, 2, 128]), ALU.add, [f'A5_{h}', 'ident'], [f'PQ_{h}'])
        for lev in range(nlev):
            for h in HS:
                N_, L_ = NL[h][:, 0:128], NL[h][:, 128:256]
                k.mm(B[h][:, 0:128], L_, N_, True, True, [f'NL_{h}'], [bk(h)])
                k.mm(B[h][:, 128:256], N_, L_, True, True, [f'NL_{h}'], [bk(h)])
            for h in HS:
                k.cp('act', NL[h][:], B[h][:, 0:256], [bk(h)], [f'NL_{h}'])
            for h in HS:
                N_, L_ = NL[h][:, 0:128], NL[h][:, 128:256]
                P_, Q_ = PQ[h][:, 0:128], PQ[h][:, 128:256]
                k.mm(B[h][:, 256:384], Q_, N_, True, True, [f'NL_{h}', f'PQ_{h}'], [bk(h)])
                k.mm(B[h][:, 384:512], P_, L_, True, True, [f'NL_{h}', f'PQ_{h}'], [bk(h)])
            for h in HS:
                k.tt('dve', PQ[h][:], B[h][:, 256:512], rd(PQ[h][:]), ALU.add, [bk(h), f'PQ_{h}'], [f'PQ_{h}'])
        for h in range(NH):
            k.mm(B[0][:, h * 64:(h + 1) * 64], A5[h][:, 256:384], vr[:, h * 64:(h + 1) * 64], True, True, [f'A5_{h}', 'vr'], [bk(0)])
        k.cp('act', W1[:], B[0][:, 0:W], [bk(0)], ['W1'])
        for h in range(NH):
            k.mm(B[1][:, h * 64:(h + 1) * 64], PQ[h][:, 0:128], W1[:, h * 64:(h + 1) * 64], True, True,
                 [f'PQ_{h}', 'W1'], [bk(1)])
        k.cp('act', U1[:], B[1][:, 0:W], [bk(1)], ['U1'])
        vsrc = vr if frc else None
        for c in range(NCK):
            cr = slice(c * CH, (c + 1) * CH)
            for h in range(NH):
                k.mm(B[2][cr, h * 64:(h + 1) * 64], lhc(FT[h][:, 0, cr]), ST[h][:], True, True, [f'FT{h}', f'ST{h}'], [bk(2)])
            k.cp('act', P1s[cr, :], B[2][cr, 0:W], [bk(2)], ['P1s'])
            for h in range(NH):
                k.mm(B[3][cr, h * 64:(h + 1) * 64], lhc(PQ[h][:, cr]), P1s[:, h * 64:(h + 1) * 64], True, True,
                     [f'PQ_{h}', 'P1s'], [bk(3)])
            k.tt('dve', Us[cr, :], B[3][cr, 0:W], U1[cr, :], ALU.add, [bk(3), 'U1'], ['Us'])
            for h in range(NH):
                hc_ = slice(h * 64, (h + 1) * 64)
                vh = vr[:, hc_] if frc else pm[:, 2 * W + h * 64:2 * W + (h + 1) * 64]
                vk = 'vr' if frc else 'pm'
                k.mm(B[6][cr, hc_], lhc(FT[h][:, 3, cr]), ST[h][:], True, False, [f'FT{h}', f'ST{h}'], [bk(6)])
                k.mm(B[6][cr, hc_], lhc(A5[h][:, 384:512][:, cr]), Us[:, hc_], False, False, [f'A5b_{h}', 'Us'], [bk(6)])
                k.mm(B[6][cr, hc_], lhc(A5[h][:, 512:640][:, cr]), vh, False, True, [f'A5b_{h}', vk], [bk(6)])
            for h in range(NH):
                hc_ = slice(h * 64, (h + 1) * 64)
                vh = pm[:, 2 * W + h * 64:2 * W + (h + 1) * 64]
                k.mm(B[7][0:64, hc_], Bfm[c][:, hc_], rdc(Us[:, hc_]), True, False, [f'Bfm{c}', 'Us'], [bk(7)])
                k.mm(B[7][0:64, hc_], Kfm[c][:, hc_], vh, False, True, [f'Kfm{c}', 'pm'], [bk(7)])
            for h in range(NH):
                hc_ = slice(h * 64, (h + 1) * 64)
                k.stt(ST[h][:], rdc(ST[h][:]), E1T[:, h, (c + 1) * CH - 1:(c + 1) * CH], B[7][0:64, hc_], ALU.mult, ALU.add,
                      [f'ST{h}', 'E1T', bk(7)], [f'ST{h}'])
        k.cp('act', ysb[:], B[6][:, 0:W], [bk(6)], ['ysb'])
        k.P.op('dve', lambda e: e.tensor_reduce(out=m4[:], in_=v3(ysb[:]), axis=AX.X, op=ALU.add), reads=['ysb'], writes=['m4'])
        k.ts('dve', m4[:], m4[:], -1.0 / 64.0, None, ALU.mult, None, ['m4'], ['m4'])
        k.tt('dve', v3(yc[:]), v3(ysb[:]), bc4(m4[:]), ALU.add, ['ysb', 'm4'], ['yc'])
        k.tt('pool', sq[:], yc[:], yc[:], ALU.mult, ['yc'], ['sq'])
        k.P.op('dve', lambda e: e.tensor_reduce(out=r4[:], in_=v3(sq[:]), axis=AX.X, op=ALU.add), reads=['sq'], writes=['r4'])
        k.ts('dve', r4[:], r4[:], 1.0 / 64.0, GN_EPS, ALU.mult, ALU.add, ['r4'], ['r4'])
        k.act(r4[:], r4[:], AF.Sqrt, ['r4'], ['r4'])
        k.recip(r4[:], r4[:], ['r4'], ['r4'])
        k.tt('dve', v3(yc[:]), v3(yc[:]), bc4(r4[:]), ALU.mult, ['yc', 'r4'], ['yc'])
        k.tt('pool', yc[:], yc[:], lngbc[:], ALU.mult, ['yc', VK[5]], ['yc'])
        k.tt('pool', yc[:], yc[:], lnbbc[:], ALU.add, ['yc', VK[6]], ['yc'])
        k.tt('dve', v3(tmp[:]), v3(v_), bc4(bon[:]), ALU.mult, ['pm', 'bon', 'tmp'], ['tmp'])
        k.tt('pool', yc[:], yc[:], tmp[:], ALU.add, ['yc', 'tmp'], ['yc'])
        k.tt('dve', ot[b][:], yc[:], gv[:], ALU.mult, ['yc', 'gv'], [f'ot{b}'])
        k.dma('pool', oc[rows, :], ot[b][:], r=[f'ot{b}'], final=True)
    return k.finish()


def build_RWKVP(L, k=None, CH=64):
    NH, fr = 8, True
    k = k or K()
    NT = L // 128
    W = NH * 64
    NG = NH // 4
    FR = mybir.dt.float32r if fr else F32
    rd = (lambda ap: ap.bitcast(F32)) if fr else (lambda ap: ap)
    NCK = 128 // CH
    nlev = 5 if CH == 64 else 6
    frc = fr and CH == 128
    FRC = mybir.dt.float32r if frc else F32
    rdc = (lambda ap: ap.bitcast(F32)) if frc else (lambda ap: ap)
    lhc = (lambda ap: ap) if frc else rd
    prkv = [k.din(nm, [L, W]) for nm in ("pr", "pk", "pv")]
    mu1 = k.din("mu1", [3 * W])
    pls = [k.din("plw", [64, L]), k.din("pla", [64, L]), k.din("plg", [128, L])]
    mul = k.din("mul", [128, 3])
    w2 = k.din("w2", [64, W])
    a2 = k.din("a2", [64, W])
    g2 = k.din("g2", [128, W])
    vecs = k.din("vecs", [7, W])
    ident_d = k.din("ident", [128, 128])
    triw_d = k.din("triw", [3, 128, 128])
    mask5_d = k.din("mask5", [128, 640])
    rowm_d = k.din("rowm", [128, 2])
    oc = k.dout("oc", [L, W])

    k.consts(ident_d)
    triw = k.sb("triw_s", [128, 3, 128])
    k.dma('sp', triw[:], triw_d.rearrange("a p n -> p a n"), w=['triw'])
    mask5 = k.sb("mask5_s", [128, 640])
    k.dma('sp', mask5[:], mask5_d, w=['mask5'])
    rowm = k.sb("rowm_s", [128, 2])
    k.dma('sp', rowm[:], rowm_d, w=['rowm'])
    mu1bc = k.bcast_row("mu1bc", mu1, 3 * W)
    vb = [k.bcast_row(f"vb{i}", vecs[i], W) for i in range(7)]
    w0bc, a0bc, kkbc, kabc, rkbc, lngbc, lnbbc = vb
    VK = [f"vb{i}" for i in range(7)]
    muls = k.sb("muls", [128, 3])
    k.dma('sp', muls[:], mul, w=['muls'])
    w2s = k.sb("w2s", [64, W])
    a2s = k.sb("a2s", [64, W])
    k.dma('sp', w2s[:], w2, w=['w2s'])
    k.dma('sp', a2s[:], a2, w=['a2s'])
    g2s = k.sb("g2s", [128, W])
    k.dma('sp', g2s[:], g2, w=['g2s'])
    ST = [k.sb(f"ST{i}", [64, 64], FRC) for i in range(NH)]
    zt = k.sb("zt", [128, W])
    k.memset('dve', zt[:], 0.0, ['zt'])
    for i in range(NH):
        k.cp('dve', ST[i][:], zt[0:64, 0:64], ['zt'], [f'ST{i}'])
    P1s = k.sb("P1s", [128, W], FRC)
    Us = k.sb("Us", [128, W], FRC)
    k.cp('dve', P1s[:], zt[:], ['zt'], ['P1s'])
    k.cp('dve', Us[:], zt[:], ['zt'], ['Us'])

    pt = [k.sb("pt0", [128, 3 * W])] * 2
    pp = [k.sb("pp0", [128, 3 * W])] * 2
    lt = [k.sb("lt0", [128, 3, 128])] * 2
    lp = [k.sb("lp0", [128, 3, 128])] * 2
    k.memset('pool', lt[0][:], 0.0, ['lt0', 'lt1', 'lt2'])
    k.memset('pool', lp[0][:], 0.0, ['lp0', 'lp1', 'lp2', 'lpz'])
    pm2 = [k.sb(f"pm{i_}", [128, 3 * W]) for i_ in range(2)]
    vr2 = [k.sb(f"vr{i_}", [128, W], FR) for i_ in range(2)]
    lm2 = [k.sb(f"lm{i_}", [128, 3, 128]) for i_ in range(2)]
    sw = k.sb("sw", [128, W])
    av = k.sb("av", [128, W])
    gv2 = [k.sb(f"gv{i_}", [128, W]) for i_ in range(2)]
    kkr = k.sb("kkr", [128, W])
    sq = k.sb("sq", [128, W])
    s4 = k.sb("s4", [128, NH])
    rn = k.sb("rn", [128, NH])
    nkk = k.sb("nkk", [128, W])
    kmod = k.sb("kmod", [128, W])
    kka = k.sb("kka", [128, W])
    tmp = k.sb("tmp", [128, W])
    bon2 = [k.sb(f"bon{i_}", [128, NH]) for i_ in range(2)]
    E1 = k.sb("E1", [128, W])
    E2 = k.sb("E2", [128, W])
    E3 = k.sb("E3", [128, W])
    E4 = k.sb("E4", [128, W])
    E1T2 = [k.sb(f"E1T{i_}", [64, NH, 128]) for i_ in range(2)]
    At2 = [k.sb(f"At{i_}", [128, W]) for i_ in range(2)]
    Bs2 = [k.sb(f"Bs{i_}", [128, W]) for i_ in range(2)]
    Ks2 = [k.sb(f"Ks{i_}", [128, W]) for i_ in range(2)]
    Rt2 = [k.sb(f"Rt{i_}", [128, W]) for i_ in range(2)]
    Bfm2 = [[k.sb(f"Bfm{p_}{c}", [128, W]) for c in range(NCK)] for p_ in range(2)]
    Kfm2 = [[k.sb(f"Kfm{p_}{c}", [128, W]) for c in range(NCK)] for p_ in range(2)]
    sqp = k.sb("sqp", [128, W])
    tmpp = k.sb("tmpp", [128, W])
    FT = [k.sb(f"FT{h}", [64, 4, 128], FR) for h in range(NH)]
    A5 = [k.sb(f"A5_{h}", [128, 640], FR) for h in range(NH)]
    NL = [k.sb(f"NL_{h}", [128, 256], FR) for h in range(NH)]
    PQ = [k.sb(f"PQ_{h}", [128, 128], FR) for h in range(NH)]
    W1 = k.sb("W1", [128, W], FR)
    U1 = k.sb("U1", [128, W])
    ysb = k.sb("ysb", [128, W])
    yc = k.sb("yc", [128, W])
    m4 = k.sb("m4", [128, NH])
    r4 = k.sb("r4", [128, NH])
    ot = [k.sb(f"ot{i}", [128, W]) for i in range(2)]
    B = [k.ps(f"psB{i}", [128, 512]) for i in range(8)]
    bk = lambda i: f'psB{i}'
    v3 = lambda t: t.rearrange("p (h j) -> p h j", h=NH)
    bc4 = lambda t: t.unsqueeze(2).broadcast_to([128, NH, 64])


    S0, S1, C0, C1 = 6, 7, 4, 5

    def tile(i):
        b = i % 2
        pm, lm = pm2[b], lm2[b]
        kpm, klm = f'pm{b}', f'lm{b}'
        At, Bs, Ks, Rt, gv, vr, bon, E1T, Bf, Kf = At2[b], Bs2[b], Ks2[b], Rt2[b], gv2[b], vr2[b], bon2[b], E1T2[b], Bfm2[b], Kfm2[b]
        kAt, kBs, kKs, kRt, kgv, kvr, kbon, kE1T, kBf, kKf = (f'{n_}{b}' for n_ in ('At', 'Bs', 'Ks', 'Rt', 'gv', 'vr', 'bon', 'E1T', 'Bf', 'Kf'))
        rows = slice(i * 128, (i + 1) * 128)
        PK, PPK, LTK, LPK = [], [], [], []
        for q in range(3):
            cq = slice(q * W, (q + 1) * W)
            k.dma('sp', pt[b][:, cq], prkv[q][rows, :], w=[f'pt{q}'])
            PK.append(f'pt{q}')
            if i == 0:
                k.dma('sp', pp[b][1:128, cq], prkv[q][0:127, :], w=[f'pp{q}'])
            else:
                k.dma('sp', pp[b][:, cq], prkv[q][i * 128 - 1:i * 128 + 127, :], w=[f'pp{q}'])
            PPK.append(f'pp{q}')
            nr = pls[q].shape[0]
            k.dma('sp', lt[b][0:nr, q, :], pls[q][:, rows], w=[f'lt{q}'])
            LTK.append(f'lt{q}')
            if i == 0:
                k.dma('sp', lp[b][0:nr, q, 1:128], pls[q][:, 0:127], w=[f'lp{q}'])
            else:
                k.dma('sp', lp[b][0:nr, q, :], pls[q][:, i * 128 - 1:i * 128 + 127], w=[f'lp{q}'])
            LPK.append(f'lp{q}')
        if i == 0:
            k.memset('pool', pp[b][0:1, :], 0.0, ['ppz'])
            k.memset('pool', lp[b][:, :, 0:1], 0.0, ['lpz'])
            PPK.append('ppz')
            LPK.append('lpz')
        k.tt('pool', pm[:], pp[b][:], pt[b][:], ALU.subtract, PPK + PK, [kpm])
        k.tt('pool', pm[:], pm[:], mu1bc[:], ALU.mult, [kpm, 'mu1bc'], [kpm])
        k.tt('pool', pm[:], pm[:], pt[b][:], ALU.add, [kpm] + PK, [kpm])
        r_, k_, v_ = pm[:, 0:W], pm[:, W:2 * W], pm[:, 2 * W:3 * W]
        LK = LTK + LPK
        k.tt('dve', lm[:], lp[b][:], lt[b][:], ALU.subtract, LK, [klm])
        for blk in range(3):
            k.stt(lm[:, blk, :], lm[:, blk, :], muls[:, blk:blk + 1], lt[b][:, blk, :], ALU.mult, ALU.add,
                  [klm, 'muls'] + LK, [klm])
        k.act(lm[0:64, 0, :], lm[0:64, 0, :], AF.Tanh, [klm], [klm])
        k.act(lm[:, 2, :], lm[:, 2, :], AF.Sigmoid, [klm], [klm])
        yield
        k.cp('act', vr[:], v_, [kpm], [kvr])
        k.mm(B[S0][:, 0:W], lm[0:64, 0, :], w2s[:], True, True, [klm, 'w2s'], [bk(S0)])
        k.mm(B[S1][:, 0:W], lm[0:64, 1, :], a2s[:], True, True, [klm, 'a2s'], [bk(S1)])
        k.tt('dve', sw[:], B[S0][:, 0:W], w0bc[:], ALU.add, [bk(S0), VK[0]], ['sw'])
        k.act(sw[:], sw[:], AF.Sigmoid, ['sw'], ['sw'])
        k.tt('dve', av[:], B[S1][:, 0:W], a0bc[:], ALU.add, [bk(S1), VK[1]], ['av'])
        k.act(av[:], av[:], AF.Sigmoid, ['av'], ['av'])
        k.mm(B[S0][:, 0:W], lm[:, 2, :], g2s[:], True, True, [klm, 'g2s'], [bk(S0)])
        k.cp('act', gv[:], B[S0][:, 0:W], [bk(S0)], [kgv])
        yield
        k.tt('pool', kkr[:], k_, kkbc[:], ALU.mult, [kpm, VK[2]], ['kkr'])
        k.tt('pool', sq[:], kkr[:], kkr[:], ALU.mult, ['kkr'], ['sq'])
        k.P.op('dve', lambda e: e.tensor_reduce(out=s4[:], in_=v3(sq[:]), axis=AX.X, op=ALU.add), reads=['sq'], writes=['s4'])
        k.act(s4[:], s4[:], AF.Sqrt, ['s4'], ['s4'])
        k.ts('dve', s4[:], s4[:], 1e-12, None, ALU.max, None, ['s4'], ['s4'])
        k.recip(rn[:], s4[:], ['s4'], ['rn'])
        k.ts('dve', rn[:], rn[:], -1.0, None, ALU.mult, None, ['rn'], ['rn'])
        k.tt('dve', v3(nkk[:]), v3(kkr[:]), bc4(rn[:]), ALU.mult, ['kkr', 'rn'], ['nkk'])
        k.stt(tmp[:], av[:], -1.0, kabc[:], ALU.add, ALU.mult, ['av', VK[3]], ['tmp'])
        k.stt(kmod[:], tmp[:], 1.0, k_, ALU.add, ALU.mult, ['tmp', kpm], ['kmod'])
        k.stt(kka[:], nkk[:], -1.0, av[:], ALU.mult, ALU.mult, ['nkk', 'av'], ['kka'])
        k.tt('pool', tmp[:], r_, kmod[:], ALU.mult, [kpm, 'kmod', 'tmp'], ['tmp'])
        k.tt('pool', tmp[:], tmp[:], rkbc[:], ALU.mult, ['tmp', VK[4]], ['tmp'])
        k.P.op('dve', lambda e: e.tensor_reduce(out=bon[:], in_=v3(tmp[:]), axis=AX.X, op=ALU.add), reads=['tmp'], writes=[kbon])
        k.mm(B[S1][:, 0:W], triw[:, 0, :], sw[:], True, True, ['triw', 'sw'], [bk(S1)])
        k.mm(B[S0][:, 0:W], triw[:, 1, :], sw[:], True, True, ['triw', 'sw'], [bk(S0)])
        k.act(E1[:], B[S1][:, 0:W], AF.Exp, [bk(S1)], ['E1'])
        k.act(E2[:], B[S1][:, 0:W], AF.Exp, [bk(S1)], ['E2'], scale=-1.0)
        k.act(E3[:], B[S0][:, 0:W], AF.Exp, [bk(S0)], ['E3'])
        k.mm(B[S1][:, 0:W], triw[:, 2, :], sw[:], True, True, ['triw', 'sw'], [bk(S1)])
        k.act(E4[:], B[S1][:, 0:W], AF.Exp, [bk(S1)], ['E4'])
        for g in range(2):
            for hl in range(4):
                h = 4 * g + hl
                k.mm(B[S0 + g][0:64, hl * 128:(hl + 1) * 128], sw[:, h * 64:(h + 1) * 64], triw[:, 0, :], True, True,
                     ['sw', 'triw'], [bk(S0 + g)])
        for g in range(2):
            k.act(E1T[:, 4 * g:4 * g + 4, :].rearrange("p a t -> p (a t)"), B[S0 + g][0:64, :], AF.Exp, [bk(S0 + g)], [kE1T])
        yield
        k.tt('dve', At[:], nkk[:], E3[:], ALU.mult, ['nkk', 'E3'], [kAt])
        k.tt('pool', Bs[:], kka[:], E2[:], ALU.mult, ['kka', 'E2'], [kBs])
        k.tt('dve', Ks[:], kmod[:], E2[:], ALU.mult, ['kmod', 'E2'], [kKs])
        k.tt('pool', Rt[:], r_, E1[:], ALU.mult, [kpm, 'E1'], [kRt])
        for c in range(NCK):
            k.stt(Bf[c][:], kka[:], rowm[:, c:c + 1], E4[:], ALU.mult, ALU.mult, ['kka', 'E4', 'rowm'], [kBf])
            k.stt(Kf[c][:], kmod[:], rowm[:, c:c + 1], E4[:], ALU.mult, ALU.mult, ['kmod', 'E4', 'rowm'], [kKf])
        yield
        for g in range(2):
            HS = list(range(4 * g, 4 * g + 4))
            for h in HS:
                hl = h % 4
                cs_ = slice(h * 64, (h + 1) * 64)
                for q, (src, key) in enumerate([(At, kAt), (Bs, kBs), (Ks, kKs), (Rt, kRt)]):
                    k.tr(B[hl][0:64, q * 128:(q + 1) * 128], src[:, cs_], k.identf[:], [key], [bk(hl)])
            for h in HS:
                hl = h % 4
                k.cp('act' if h % 2 else 'dve', FT[h][:].rearrange("p a t -> p (a t)"), B[hl][0:64, :], [bk(hl)], [f'FT{h}'])
            for h in HS:
                hl = h % 4
                AtT, BsT, KsT, RtT = (FT[h][:, q, :] for q in range(4))
                k.mm(B[hl][:, 0:128], BsT, AtT, True, True, [f'FT{h}'], [bk(hl)])
                k.mm(B[hl][:, 128:256], AtT, BsT, True, True, [f'FT{h}'], [bk(hl)])
                k.mm(B[hl][:, 256:384], KsT, AtT, True, True, [f'FT{h}'], [bk(hl)])
            for h in HS:
                hl = h % 4
                k.tt('dve', A5[h][:, 0:384], B[hl][:, 0:384], mask5[:, 0:384], ALU.mult, [bk(hl), 'mask5'], [f'A5_{h}'])
            for h in HS:
                hl = h % 4
                AtT, BsT, KsT, RtT = (FT[h][:, q, :] for q in range(4))
                k.mm(B[hl][:, 0:128], BsT, RtT, True, True, [f'FT{h}'], [bk(hl)])
                k.mm(B[hl][:, 128:256], KsT, RtT, True, True, [f'FT{h}'], [bk(hl)])
            for h in HS:
                hl = h % 4
                k.tt('dve', A5[h][:, 384:640], B[hl][:, 0:256], mask5[:, 384:640], ALU.mult, [bk(hl), 'mask5'], [f'A5b_{h}'])
                k.cp('act', NL[h][:], rd(A5[h][:, 0:256]), [f'A5_{h}'], [f'NL_{h}'])
                k.tt('dve', PQ[h][:, 0:128], rd(A5[h][:, 0:128]), k.identf[:], ALU.add, [f'A5_{h}', 'ident'], [f'PQ_{h}'])
            for lev in range(nlev):
                last = (lev == nlev - 1)
                for h in HS:
                    hl = h % 4
                    N_, L_ = NL[h][:, 0:128], NL[h][:, 128:256]
                    k.mm(B[hl][:, 0:128], L_, N_, True, True, [f'NL_{h}'], [bk(hl)])
                    k.mm(B[hl][:, 128:256], N_, L_, True, True, [f'NL_{h}'], [bk(hl)])
                for h in HS:
                    hl = h % 4
                    k.cp('act', NL[h][:], B[hl][:, 0:256], [bk(hl)], [f'NL_{h}'])
                for h in HS:
                    hl = h % 4
                    k.mm(B[hl][:, 256:384], NL[h][:, 128:256], PQ[h][:, 0:128], True, True, [f'NL_{h}', f'PQ_{h}'], [bk(hl)])
                for h in HS:
                    hl = h % 4
                    k.tt('dve', PQ[h][:, 0:128], B[hl][:, 256:384], rd(PQ[h][:, 0:128]), ALU.add, [bk(hl), f'PQ_{h}'], [f'PQ_{h}'])
            yield
        for h in range(NH):
            k.mm(B[C0][:, h * 64:(h + 1) * 64], A5[h][:, 256:384], vr[:, h * 64:(h + 1) * 64], True, True, [f'A5_{h}', kvr], [bk(C0)])
        k.cp('act', W1[:], B[C0][:, 0:W], [bk(C0)], ['W1'])
        for h in range(NH):
            k.mm(B[C1][:, h * 64:(h + 1) * 64], PQ[h][:, 0:128], W1[:, h * 64:(h + 1) * 64], True, True,
                 [f'PQ_{h}', 'W1'], [bk(C1)])
        k.cp('act', U1[:], B[C1][:, 0:W], [bk(C1)], ['U1'])
        for c in range(NCK):
            cr = slice(c * CH, (c + 1) * CH)
            for h in range(NH):
                k.mm(B[C0][cr, h * 64:(h + 1) * 64], lhc(FT[h][:, 0, cr]), ST[h][:], True, True, [f'FT{h}', f'ST{h}'], [bk(C0)])
            k.cp('act', P1s[cr, :], B[C0][cr, 0:W], [bk(C0)], ['P1s'])
            for h in range(NH):
                k.mm(B[C0][cr, h * 64:(h + 1) * 64], lhc(PQ[h][:, cr]), P1s[:, h * 64:(h + 1) * 64], True, True,
                     [f'PQ_{h}', 'P1s'], [bk(C0)])
            k.tt('dve', Us[cr, :], B[C0][cr, 0:W], U1[cr, :], ALU.add, [bk(C0), 'U1'], ['Us'])
            for h in range(NH):
                hc_ = slice(h * 64, (h + 1) * 64)
                vh = vr[:, hc_] if frc else rd(vr[:, hc_])
                k.mm(B[C0][cr, hc_], lhc(FT[h][:, 3, cr]), ST[h][:], True, False, [f'FT{h}', f'ST{h}'], [bk(C0)])
                k.mm(B[C0][cr, hc_], lhc(A5[h][:, 384:512][:, cr]), Us[:, hc_], False, False, [f'A5b_{h}', 'Us'], [bk(C0)])
                k.mm(B[C0][cr, hc_], lhc(A5[h][:, 512:640][:, cr]), vh, False, True, [f'A5b_{h}', kvr], [bk(C0)])
            for h in range(NH):
                hc_ = slice(h * 64, (h + 1) * 64)
                k.mm(B[C1][0:64, hc_], Bf[c][:, hc_], rdc(Us[:, hc_]), True, False, [kBf, 'Us'], [bk(C1)])
                k.mm(B[C1][0:64, hc_], Kf[c][:, hc_], rd(vr[:, hc_]), False, True, [kKf, kvr], [bk(C1)])
            for h in range(NH):
                hc_ = slice(h * 64, (h + 1) * 64)
                k.stt(ST[h][:], rdc(ST[h][:]), E1T[:, h, (c + 1) * CH - 1:(c + 1) * CH], B[C1][0:64, hc_], ALU.mult, ALU.add,
                      [f'ST{h}', kE1T, bk(C1)], [f'ST{h}'])
        k.cp('act', ysb[:], B[C0][:, 0:W], [bk(C0)], ['ysb'])
        k.P.op('dve', lambda e: e.tensor_reduce(out=m4[:], in_=v3(ysb[:]), axis=AX.X, op=ALU.add), reads=['ysb'], writes=['m4'])
        k.ts('dve', m4[:], m4[:], -1.0 / 64.0, None, ALU.mult, None, ['m4'], ['m4'])
        k.tt('dve', v3(yc[:]), v3(ysb[:]), bc4(m4[:]), ALU.add, ['ysb', 'm4'], ['yc'])
        k.tt('pool', sqp[:], yc[:], yc[:], ALU.mult, ['yc'], ['sqp'])
        k.P.op('dve', lambda e: e.tensor_reduce(out=r4[:], in_=v3(sqp[:]), axis=AX.X, op=ALU.add), reads=['sqp'], writes=['r4'])
        k.ts('dve', r4[:], r4[:], 1.0 / 64.0, GN_EPS, ALU.mult, ALU.add, ['r4'], ['r4'])
        k.act(r4[:], r4[:], AF.Sqrt, ['r4'], ['r4'])
        k.recip(r4[:], r4[:], ['r4'], ['r4'])
        k.tt('dve', v3(yc[:]), v3(yc[:]), bc4(r4[:]), ALU.mult, ['yc', 'r4'], ['yc'])
        k.tt('pool', yc[:], yc[:], lngbc[:], ALU.mult, ['yc', VK[5]], ['yc'])
        k.tt('pool', yc[:], yc[:], lnbbc[:], ALU.add, ['yc', VK[6]], ['yc'])
        k.tt('dve', v3(tmpp[:]), v3(rd(vr[:])), bc4(bon[:]), ALU.mult, [kvr, kbon], ['tmpp'])
        k.tt('pool', yc[:], yc[:], tmpp[:], ALU.add, ['yc', 'tmpp'], ['yc'])
        k.tt('dve', ot[b][:], yc[:], gv[:], ALU.mult, ['yc', kgv], [f'ot{b}'])
        k.dma('pool', oc[rows, :], ot[b][:], r=[f'ot{b}'], final=True)

    gens = {}

    def adv(j):
        if 0 <= j < NT:
            try:
                next(gens[j])
            except StopIteration:
                pass

    for step in range(NT + 2):
        if step < NT:
            gens[step] = tile(step)
            adv(step)
        for r_i in range(3):
            adv(step - 1)
            adv(step - 2)
    return k.finish()


def rwkv_consts(CH=64):
    c = -math.exp(-0.5)
    blk = np.kron(np.eye(128 // CH), np.ones((CH, CH)))
    s_idx = np.arange(128)[:, None]
    t_idx = np.arange(128)[None, :]
    triw = np.stack([c * blk * (s_idx <= t_idx), c * blk * (s_idx < t_idx), c * blk * (s_idx > t_idx)]).astype(np.float32)
    lt_, le_, gt_ = blk * (s_idx < t_idx), blk * (s_idx <= t_idx), blk * (t_idx < s_idx)
    mask5 = np.concatenate([lt_, gt_, lt_, le_, le_], 1).astype(np.float32)
    rowm = np.stack([(np.arange(128) < 64), (np.arange(128) >= 64)], 1).astype(np.float32) if CH == 64 else np.ones((128, 2), np.float32)
    return dict(ident=np.eye(128, dtype=np.float32), triw=triw, mask5=mask5, rowm=rowm)


def rwkv_host_inputs(s, p_rwkv, prm, NH=4, CH=64):
    L = p_rwkv.shape[0]
    cs = slice(64 * NH * s, 64 * NH * (s + 1))
    r_, w1, k_, v_, a1, g1 = np.split(p_rwkv, np.cumsum([512, 64, 512, 512, 64])[:5], axis=-1)
    mu = prm['rwkv_mu']
    mur, muw1, muk, muv, mua1, mug1 = np.split(mu, np.cumsum([512, 64, 512, 512, 64])[:5])
    zm = np.zeros(64, np.float32)
    mul = np.concatenate([muw1, zm, mua1, zm, mug1]).reshape(3, 128).T
    vecs = np.stack([prm['rwkv_w0'][cs], prm['rwkv_a0'][cs], prm['rwkv_k_k'][cs], prm['rwkv_k_a'][cs],
                     prm['rwkv_r_k'].reshape(-1)[cs], prm['rwkv_ln_gain'][cs], prm['rwkv_ln_bias'][cs]])
    c_ = np.ascontiguousarray
    d = dict(pr=c_(r_[:, cs]), pk=c_(k_[:, cs]), pv=c_(v_[:, cs]),
             mu1=c_(np.concatenate([mur[cs], muk[cs], muv[cs]])),
             plw=c_(w1.T), pla=c_(a1.T), plg=c_(g1.T), mul=c_(mul),
             w2=c_(prm['rwkv_w2'][:, cs]), a2=c_(prm['rwkv_a2'][:, cs]),
             g2=c_(prm['rwkv_g2'][:, cs]), vecs=c_(vecs))
    d.update(rwkv_consts(CH))
    return d


FM0 = [(0, 128, 0), (128, 128, 128), (256, 128, 256), (384, 128, 384), (1536, 16, 512)] + \
      [(1552 + j * 128, 128, 528 + j * 128) for j in range(4)]
NF0 = 1040
FM1 = [(512, 64, 0), (1600, 64, 64), (1664, 128, 128)] + [(1792 + j * 128, 128, 256 + j * 128) for j in range(8)]
NF1 = 1280


def host_params(inp):
    c_ = lambda a: np.ascontiguousarray(np.asarray(a), dtype=np.float32)
    P = {}
    P['ident'] = np.eye(128, dtype=np.float32)
    P['triu'] = np.triu(np.ones((128, 128), np.float32))
    P['trigt'] = np.tril(np.ones((128, 128), np.float32), -1)
    for l in range(2):
        for j in range(7):
            P[f'g{l}_{j}'] = c_(inp['norm_gain'][l][j])
        for nm in ('xa_wq', 'xa_wk', 'xa_wv', 'xa_wo', 'mlp_w1', 'mlp_w2'):
            P[f'{nm}{l}'] = c_(inp[nm][l])
    P['w_in0'] = c_(inp['ab_w_in'][0])
    P['w_in1'] = c_(inp['cd_w_in'][0])
    P['w_out0'] = c_(inp['ab_w_out'][0])
    P['w_out1'] = c_(inp['cd_w_out'][0])
    P['wglu'] = c_(inp['s5_w_glu'][0])
    P['bglu'] = c_(inp['s5_b_glu'][0])
    prm0 = {k_: np.asarray(inp[k_][0]) for k_ in inp if k_.startswith('s5_') or k_.startswith('gla_')}
    prm1 = {k_: np.asarray(inp[k_][0]) for k_ in inp if k_.startswith('rwkv_') or k_.startswith('lru_')}
    for s in range(2):
        cs = slice(s * 128, (s + 1) * 128)
        P[f'gla_w2_{s}'] = c_(prm0['gla_w_decay2'][:, cs])
        P[f'gla_bd_{s}'] = c_(prm0['gla_b_decay'][None, cs])
        P[f'gla_gn_{s}'] = c_(prm0['gla_norm_gain'][2 * s:2 * s + 2].reshape(256))
        d = s5_host_inputs(s, np.zeros((2, 512), np.float32), prm0)
        for nm in ('lam_re', 'lam_im', 'lstep', 'Bre', 'Bim', 'Cre', 'Cim', 'dsk'):
            P[f's5_{nm}_{s}'] = c_(d[nm])
        P['iota_p'] = c_(d['iota_p'])
        P['iota_f'] = c_(d['iota_f'])
        if s == 0:
            d = rwkv_host_inputs(0, np.zeros((2, 1792), np.float32), prm1, 8, 64)
            for nm in ('mu1', 'mul', 'w2', 'a2', 'g2', 'vecs'):
                P[f'rw_{nm}'] = c_(d[nm])
            for nm in ('triw', 'mask5', 'rowm'):
                P[f'rw_{nm}'] = c_(d[nm])
        d = lru_host_inputs(s, np.zeros((2, 512), np.float32), np.zeros((2, 512), np.float32), prm1)
        for nm in ('cw', 'cb', 'Wa', 'Wx', 'ba', 'bx', 'lam'):
            P[f'lru_{nm}_{s}'] = c_(d[nm])
    return P


def build_fused(P, L):
    k = K(fused=True)
    X = {nm: k.xin(nm, a.shape) for nm, a in P.items()}
    x = k.xin('x', [L, D])
    mem = k.xin('mem', [256, D])
    out = k.xout('out', [L, D])
    proj0 = k.scratch('proj0', [L, 2064])
    PT0 = k.scratch('PT0', [NF0, L])
    proj1 = k.scratch('proj1', [L, 2816])
    PT1 = k.scratch('PT1', [NF1, L])
    o = k.scratch('o', [L, D])
    odT = k.scratch('odT', [512, L])
    h1 = k.scratch('h1', [L, D])
    h2 = k.scratch('h2', [L, D])
    h3 = k.scratch('h3', [L, D])

    def cblock(l, hin, hout, glu, ob_fm):
        io = dict(oa=o[:, 0:512], hin=hin, wout=X[f'w_out{l}'], g1=X[f'g{l}_1'], ident=X['ident'], hout=h1)
        if ob_fm:
            io['obT'] = odT
        else:
            io['ob'] = o[:, 512:1024]
        if glu:
            io.update(wglu=X['wglu'], bglu=X['bglu'])
        k.begin_phase(f'C1_{l}', io)
        build_C1(L, glu, k=k, ob_fm=ob_fm)
        k.begin_phase(f'C2_{l}', dict(hin=h1, mem=mem, wq=X[f'xa_wq{l}'], wk=X[f'xa_wk{l}'], wv=X[f'xa_wv{l}'], wo=X[f'xa_wo{l}'],
                                      g2=X[f'g{l}_2'], g3=X[f'g{l}_3'], g6=X[f'g{l}_6'], ident=X['ident'], hout=h2))
        build_C2(L, k=k)
        k.begin_phase(f'C3_{l}', dict(hin=h2, w1=X[f'mlp_w1{l}'], w2=X[f'mlp_w2{l}'], g4=X[f'g{l}_4'], g5=X[f'g{l}_5'],
                                      ident=X['ident'], hout=hout))
        build_C3(L, k=k)

    k.begin_phase('A0', dict(x=x, gain=X['g0_0'], W=X['w_in0'], ident=X['ident'], out=proj0, outT=PT0))
    build_A2(L, 2064, FM0, NF0, k=k)
    for s in range(2):
        io_g = dict(qT=PT0[s * 128:(s + 1) * 128, :], kT=PT0[256 + s * 128:256 + (s + 1) * 128, :],
                    ktok=proj0[:, 256 + s * 128:256 + (s + 1) * 128], v=proj0[:, 512 + s * 256:512 + (s + 1) * 256],
                    gate=proj0[:, 1024 + s * 256:1024 + (s + 1) * 256], dlrT=PT0[512:528, :],
                    w2=X[f'gla_w2_{s}'], bdec=X[f'gla_bd_{s}'], gn=X[f'gla_gn_{s}'], triu=X['triu'],
                    trigt=X['trigt'], oa=o[:, s * 256:(s + 1) * 256])
        k.begin_phase(f'GLA{s}', io_g)
        build_GLA(L, k=k)
    for s in range(2):
        io_s = dict(uT=PT0[528 + s * 256:528 + (s + 1) * 256, :], u=proj0[:, 1552 + s * 256:1552 + (s + 1) * 256],
                    triu=X['triu'], iota_p=X['iota_p'], iota_f=X['iota_f'], y=o[:, 512 + s * 256:512 + (s + 1) * 256])
        for nm in ('lam_re', 'lam_im', 'lstep', 'Bre', 'Bim', 'Cre', 'Cim', 'dsk'):
            io_s[nm] = X[f's5_{nm}_{s}']
        k.begin_phase(f'S5{s}', io_s)
        build_S5(L, k=k)
    cblock(0, x, h3, True, False)
    k.begin_phase('A1', dict(x=h3, gain=X['g1_0'], W=X['w_in1'], ident=X['ident'], out=proj1, outT=PT1))
    build_A2(L, 2816, FM1, NF1, k=k)
    io = dict(pr=proj1[:, 0:512], pk=proj1[:, 576:1088], pv=proj1[:, 1088:1600], plw=PT1[0:64, :], pla=PT1[64:128, :],
              plg=PT1[128:256, :], ident=X['ident'], triw=X['rw_triw'], mask5=X['rw_mask5'], rowm=X['rw_rowm'], oc=o[:, 0:512])
    for nm in ('mu1', 'mul', 'w2', 'a2', 'g2', 'vecs'):
        io[nm] = X[f'rw_{nm}']
    k.begin_phase('RW', io)
    build_RWKVP(L, k=k, CH=64)
    streams = []
    for s in range(2):
        io = dict(xbT=PT1[256 + s * 256:256 + (s + 1) * 256, :], gateT=PT1[768 + s * 256:768 + (s + 1) * 256, :],
                  odT=odT[s * 256:(s + 1) * 256, :])
        for nm in ('cw', 'cb', 'Wa', 'Wx', 'ba', 'bx', 'lam'):
            io[nm] = X[f'lru_{nm}_{s}']
        streams.append((f'l{s}_', io, lambda kk: gen_LRU(L, kk)))
    k.begin_phase('LRU', {})
    run_streams(k, streams)
    k.finish()
    cblock(1, h3, out, False, True)
    return k.finish_program()


BATCH, SEQ = 4, 4096
_CACHE = {}


def kernel(**inp):
    inp = {k_: np.asarray(v_) for k_, v_ in inp.items()}
    P = host_params(inp)
    if 'nc' not in _CACHE:
        _CACHE['nc'] = build_fused(P, SEQ)
    nc = _CACHE['nc']
    maps = []
    for b in range(BATCH):
        m = dict(P)
        m['x'] = np.ascontiguousarray(inp['x'][b], dtype=np.float32)
        m['mem'] = np.ascontiguousarray(inp['mem'][b], dtype=np.float32)
        maps.append(m)
    res = run_bass_kernel_spmd(nc, maps, core_ids=list(range(BATCH))).results
    return np.ascontiguousarray(np.stack([res[b]['out'] for b in range(BATCH)]).astype(np.float32))
```

```python
import os
import math
from contextlib import ExitStack


import numpy as np
import concourse.bass as bass
import concourse.mybir as mybir
from concourse.bass_utils import run_bass_kernel_spmd

F32 = mybir.dt.float32
BF16 = mybir.dt.bfloat16
I32 = mybir.dt.int32
AF = mybir.ActivationFunctionType
ALU = mybir.AluOpType
AX = mybir.AxisListType

ENGS = ['pe', 'act', 'dve', 'pool', 'sp']
NDMA_SLOTS = 8
SAME_ENGINE_SYNC = os.environ.get("NOSELF", "0") != "1"


class Prog:
    def __init__(self, nc):
        self.nc = nc
        self.ops = {e: [] for e in ENGS}
        self.cnt = {e: 0 for e in ENGS}
        self.last_w = {}
        self.readers = {}
        self.seen = {e: {} for e in ENGS}
        self.dma_n = {e: 0 for e in ENGS}
        self.dma_tok = {e: [None] * NDMA_SLOTS for e in ENGS}
        self.final_tokens = []
        from contextlib import ExitStack
        self.sem_stack = ExitStack()
        self.sems = {}
        for e in ['pe', 'act', 'dve', 'pool']:
            self.sems[('c', e)] = self.sem_stack.enter_context(nc.semaphore("s_c_" + e))
        for q in ['sp', 'pool']:
            for sl in range(NDMA_SLOTS):
                self.sems[('d', q, sl)] = self.sem_stack.enter_context(nc.semaphore(f"s_d_{q}_{sl}"))

    def barrier(self):
        toks = []
        for e in ['pe', 'act', 'dve', 'pool']:
            if self.cnt[e] > 0:
                toks.append((('c', e), self.cnt[e]))
        for q in ENGS:
            for t in self.dma_tok[q]:
                if t is not None:
                    toks.append(t)
        for e in ENGS:
            waits = []
            for (sem, val) in toks:
                if sem == ('c', e):
                    continue
                if self.seen[e].get(sem, 0) >= val:
                    continue
                waits.append((sem, val))
                self.seen[e][sem] = val
            if waits:
                self.ops[e].append((waits, None, None))
        self.last_w = {}
        self.readers = {}

    def _deps(self, eng, reads, writes):
        toks = []
        for r in reads:
            t = self.last_w.get(r)
            if t is not None:
                toks.append(t)
        for w in writes:
            t = self.last_w.get(w)
            if t is not None:
                toks.append(t)
            toks.extend(self.readers.get(w, []))
        need = {}
        for (sem, val) in toks:
            if not SAME_ENGINE_SYNC and sem == ('c', eng):
                continue
            if sem == ('c', 'pe') and eng == 'pe':
                continue
            if self.seen[eng].get(sem, 0) >= val:
                continue
            if need.get(sem, 0) < val:
                need[sem] = val
        for sem, val in need.items():
            self.seen[eng][sem] = val
        return list(need.items())

    def _commit(self, tok, reads, writes):
        for w in writes:
            self.last_w[w] = tok
            self.readers[w] = []
        for r in reads:
            if r in writes:
                continue
            self.readers.setdefault(r, []).append(tok)

    def op(self, eng, fn, reads=(), writes=()):
        self.nrec = getattr(self, 'nrec', 0) + 1
        if self.nrec > int(os.environ.get("MAXOPS", "100000000")):
            return None
        kp = getattr(self, 'key_prefix', '')
        reads = [r if r.startswith('ps') else kp + r for r in reads]
        writes = [w if w.startswith('ps') else kp + w for w in writes]
        pk = getattr(self, 'ps_prefix', '')
        reads = [('ps' + pk + r[2:]) if r.startswith('ps') else r for r in reads]
        writes = [('ps' + pk + w[2:]) if w.startswith('ps') else w for w in writes]
        writes = list(writes) + [r for r in reads if r.startswith('ps') and r not in writes]
        waits = self._deps(eng, reads, writes)
        self.cnt[eng] += 1
        tok = (('c', eng), self.cnt[eng])
        self.ops[eng].append((waits, fn, tok))
        self._commit(tok, reads, writes)
        return tok

    def dma(self, q, out, in_, reads=(), writes=(), final=False, **kw):
        self.nrec = getattr(self, 'nrec', 0) + 1
        if self.nrec > int(os.environ.get("MAXOPS", "100000000")):
            return None
        kp = getattr(self, 'key_prefix', '')
        reads = [kp + r for r in reads]
        writes = [kp + w for w in writes]
        waits = self._deps(q, reads, writes)
        n = self.dma_n[q]
        slot = n % NDMA_SLOTS
        prev = self.dma_tok[q][slot]
        if prev is not None and self.seen[q].get(prev[0], 0) < prev[1]:
            waits.append(prev)
            self.seen[q][prev[0]] = prev[1]
        tok = (('d', q, slot), 16 * (n // NDMA_SLOTS + 1))
        self.dma_n[q] += 1
        self.dma_tok[q][slot] = tok

        def fn(e, out=out, in_=in_, kw=kw):
            return e.dma_start(out=out, in_=in_, **kw)
        self.ops[q].append((waits, fn, tok))
        self._commit(tok, reads, writes)
        if final:
            self.final_tokens.append(tok)
        return tok

    def emit(self, last=True):
        nc = self.nc
        sems = self.sems
        with nc.Block() as block:
            final = list(self.final_tokens) if last else []

            def run(e, name):
                for waits, fn, tok in self.ops[name]:
                    for (s, v) in waits:
                        e.wait_ge(sems[s], v)
                    if fn is None:
                        continue
                    inst = fn(e)
                    inc = 16 if tok[0][0] == 'd' else 1
                    inst.then_inc(sems[tok[0]], inc)
                if name == 'sp':
                    for (s, v) in final:
                        e.wait_ge(sems[s], v)
                self.ops[name] = []

            @block.tensor
            def _(e):
                run(e, 'pe')

            @block.scalar
            def _(e):
                run(e, 'act')

            @block.vector
            def _(e):
                run(e, 'dve')

            @block.gpsimd
            def _(e):
                run(e, 'pool')

            @block.sync
            def _(e):
                run(e, 'sp')
        if last:
            self.sem_stack.close()


D = 1024
KC = 8
EPS = 1e-6


class K:
    def __init__(self, fused=False):
        self.nc = bass.Bass("TRN2", target_bir_lowering=False)
        self.st = ExitStack()
        self.P = Prog(self.nc)
        self.n = 0
        self.fused = fused
        self.io = {}
        self.pfx = ""

    def begin_phase(self, name, io):
        self.pfx = name + "_"
        self.io = io
        self.st = ExitStack()
        for a in ('wstage', 'rr_cache', 'identf', 'identb'):
            if hasattr(self, a):
                delattr(self, a)

    def scratch(self, name, shape, dt=F32):
        return self.nc.dram_tensor(name, list(shape), dt, kind="Internal").ap()

    def xin(self, name, arr_shape, dt=F32):
        return self.nc.dram_tensor(name, list(arr_shape), dt, kind="ExternalInput").ap()

    def xout(self, name, arr_shape, dt=F32):
        return self.nc.dram_tensor(name, list(arr_shape), dt, kind="ExternalOutput").ap()

    def din(self, name, shape, dt=F32):
        if self.fused:
            ap = self.io[name]
            assert list(ap.shape) == list(shape), (name, ap.shape, shape)
            return ap
        return self.nc.dram_tensor(name, list(shape), dt, kind="ExternalInput").ap()

    def dout(self, name, shape, dt=F32):
        if self.fused:
            ap = self.io[name]
            assert list(ap.shape) == list(shape), (name, ap.shape, shape)
            return ap
        return self.nc.dram_tensor(name, list(shape), dt, kind="ExternalOutput").ap()

    def sb(self, name, shape, dt=F32):
        pers = getattr(self, 'persist', None)
        if pers is not None and (self.pfx + name) in pers:
            return pers[self.pfx + name]
        return self.st.enter_context(self.nc.sbuf_tensor(self.pfx + name, list(shape), dt))

    def push_scope(self, persistent):
        self.persist = getattr(self, 'persist', None) or {}
        for (name, shape, dt) in persistent:
            self.persist[self.pfx + name] = self.st.enter_context(self.nc.sbuf_tensor(self.pfx + name, list(shape), dt))
        self._st_saved = self.st
        self.st = ExitStack()

    def pop_scope(self):
        self.P.barrier()
        self.P.emit(last=False)
        self.st.close()
        self.st = self._st_saved

    def ps(self, name, shape, dt=F32):
        return self.st.enter_context(self.nc.psum_tensor(self.pfx + name, list(shape), dt))

    def finish(self, last=True):
        if self.fused:
            self.P.barrier()
            self.P.emit(last=False)
            self.st.close()
            return None
        self.P.emit()
        self.st.close()
        return self.nc

    def finish_program(self):
        self.P.emit(last=True)
        return self.nc

    def mm(self, out, lhsT, rhs, start, stop, r, w):
        self.P.op('pe', lambda e: e.matmul(out, lhsT=lhsT, rhs=rhs, start=start, stop=stop), reads=r, writes=w)

    def tr(self, out, in_, ident, r, w):
        self.P.op('pe', lambda e: e.transpose(out=out, in_=in_, identity=ident), reads=list(r) + ['ident'], writes=w)

    def act(self, out, in_, func, r, w, **kw):
        self.P.op('act', lambda e: e.activation(out=out, in_=in_, func=func, **kw), reads=r, writes=w)

    def tt(self, eng, out, in0, in1, op, r, w):
        self.P.op(eng, lambda e: e.tensor_tensor(out=out, in0=in0, in1=in1, op=op), reads=r, writes=w)

    def ts(self, eng, out, in0, s1, s2, op0, op1, r, w):
        if op1 is None:
            self.P.op(eng, lambda e: e.tensor_scalar(out=out, in0=in0, scalar1=s1, scalar2=None, op0=op0), reads=r, writes=w)
        else:
            self.P.op(eng, lambda e: e.tensor_scalar(out=out, in0=in0, scalar1=s1, scalar2=s2, op0=op0, op1=op1), reads=r, writes=w)

    def stt(self, out, in0, scalar, in1, op0, op1, r, w):
        self.P.op('dve', lambda e: e.scalar_tensor_tensor(out=out, in0=in0, scalar=scalar, in1=in1, op0=op0, op1=op1),
                  reads=r, writes=w)

    def cp(self, eng, out, in_, r, w):
        if eng == 'act':
            self.P.op('act', lambda e: e.copy(out=out, in_=in_), reads=r, writes=w)
        else:
            self.P.op(eng, lambda e: e.tensor_copy(out=out, in_=in_), reads=r, writes=w)

    def recip(self, out, in_, r, w):
        self.P.op('dve', lambda e: e.reciprocal(out=out, in_=in_), reads=r, writes=w)

    def memset(self, eng, ap, val, w):
        self.P.op(eng, lambda e: e.memset(ap, val), reads=[], writes=w)

    def dma(self, q, out, in_, r=(), w=(), final=False, **kw):
        self.P.dma(q, out, in_, reads=r, writes=w, final=final, **kw)

    def consts(self, ident_d):
        self.identf = self.sb("identf", [128, 128], F32)
        self.identb = self.sb("identb", [128, 128], BF16)
        self.dma('sp', self.identf[:], ident_d, w=['ident'])
        self.cp('dve', self.identb[:], self.identf[:], ['ident'], ['ident'])

    def gain_cols(self, name, g_d):
        t = self.sb(name, [128, KC], F32)
        self.dma('sp', t[:], g_d.rearrange("(kc p) -> p kc", p=128), w=[name], allow_slow_non_contiguous=True)
        return t

    def bcast_row(self, name, vec_d, n):
        t = self.sb(name, [128, n], F32)
        self.dma('sp', t[:], vec_d.partition_broadcast(128), w=[name])
        return t

    def load_weight(self, name, w_d, kchunks, ncols, gcol=None, gkey=None, stage_cols=2048, q='sp'):
        wb = self.sb(name, [128, kchunks, ncols], BF16)
        if not hasattr(self, 'wstage'):
            self.wstage = [self.sb(f"wstage{i}", [128, stage_cols], F32) for i in range(2)]
            self.wstage_n = 0
            self.wstage_cols = stage_cols
        sc = self.wstage_cols
        wv = w_d.rearrange("(kc p) n -> p kc n", p=128)
        for kc in range(kchunks):
            for c0 in range(0, ncols, sc):
                cw = min(sc, ncols - c0)
                b = self.wstage_n % 2
                self.wstage_n += 1
                stg = self.wstage[b]
                self.dma(q, stg[:, 0:cw], wv[:, kc, c0:c0 + cw], w=[f'wstage{b}'])
                eng = 'act' if (kc % 2 == 0) else 'dve'
                if gcol is not None:
                    if eng == 'act':
                        self.act(wb[:, kc, c0:c0 + cw], stg[:, 0:cw], AF.Copy, [f'wstage{b}', gkey], [f'{name}{kc}'],
                                 scale=gcol[:, kc:kc + 1])
                    else:
                        self.ts('dve', wb[:, kc, c0:c0 + cw], stg[:, 0:cw], gcol[:, kc:kc + 1], None, ALU.mult, None,
                                [f'wstage{b}', gkey], [f'{name}{kc}'])
                else:
                    self.cp(eng, wb[:, kc, c0:c0 + cw], stg[:, 0:cw], [f'wstage{b}'], [f'{name}{kc}'])
        return wb

    def rstd_of(self, x_ap, xkey, ss, rstd, junk, key, ncols=D):
        self.act(junk, x_ap, AF.Square, [xkey], ['junk', key + 'ss'], accum_out=ss)
        self.ts('dve', rstd, ss, 1.0 / ncols, EPS, ALU.mult, ALU.add, [key + 'ss'], [key])
        self.act(rstd, rstd, AF.Sqrt, [key], [key])
        self.recip(rstd, rstd, [key], [key])


def pipeline(make_gen, n):
    active = []
    for i in range(n):
        for g in list(active):
            try:
                next(g)
            except StopIteration:
                active.remove(g)
        g = make_gen(i)
        active.append(g)
        try:
            next(g)
        except StopIteration:
            active.remove(g)
    while active:
        for g in list(active):
            try:
                next(g)
            except StopIteration:
                active.remove(g)


def pipeline_gen(make_gen, n):
    active = []
    for i in range(n):
        for g in list(active):
            try:
                next(g)
            except StopIteration:
                active.remove(g)
        g = make_gen(i)
        active.append(g)
        try:
            next(g)
        except StopIteration:
            active.remove(g)
        yield
    while active:
        for g in list(active):
            try:
                next(g)
            except StopIteration:
                active.remove(g)
        yield


def run_streams(k, streams):
    base_pfx = k.pfx
    gens = []
    for (pf, io, gf) in streams:
        gens.append([pf, io, None, gf])
    active = list(gens)
    while active:
        for st in list(active):
            pf, io, g, gf = st
            k.pfx = base_pfx + pf
            k.P.key_prefix = pf
            k.P.ps_prefix = pf
            k.io = io
            try:
                if g is None:
                    st[2] = gf(k)
                    g = st[2]
                next(g)
            except StopIteration:
                active.remove(st)
    k.pfx = base_pfx
    k.P.key_prefix = ''
    k.P.ps_prefix = ''


GELU_C = 1.5957691216057308


def norm_T(k, xt, xkey, xn, xnkey, xT_dst, xTkey, psT, psTkey, ss, rstd, junk, key, evac_eng='act'):
    k.rstd_of(xt, xkey, ss, rstd, junk, key)
    k.ts('dve', xn, xt, rstd, None, ALU.mult, None, [xkey, key], [xnkey])
    for kc in range(KC):
        k.tr(psT[:, kc * 128:(kc + 1) * 128], xn[:, kc * 128:(kc + 1) * 128], k.identb[:], [xnkey], [psTkey])
    k.cp(evac_eng, xT_dst, psT[:].rearrange("p (k t) -> p k t", k=KC), [psTkey], [xTkey])


def post_norm_res(k, ps2, pskeys, ht, hkey, gbc, gkey, tmp2, tmpkeys, ss2, rstd, junk, key):
    for j in range(2):
        k.act(junk[:, 0:512], ps2[j], AF.Square, [pskeys[j]], ['junk', key + f'ss{j}'], accum_out=ss2[:, j:j + 1])
    k.tt('dve', ss2[:, 0:1], ss2[:, 0:1], ss2[:, 1:2], ALU.add, [key + 'ss0', key + 'ss1'], [key + 'ss0'])
    k.ts('dve', rstd, ss2[:, 0:1], 1.0 / D, EPS, ALU.mult, ALU.add, [key + 'ss0'], [key])
    k.act(rstd, rstd, AF.Sqrt, [key], [key])
    k.recip(rstd, rstd, [key], [key])
    for j in range(2):
        sl = slice(j * 512, (j + 1) * 512)
        k.stt(tmp2[j], ps2[j], rstd, gbc[:, sl], ALU.mult, ALU.mult, [pskeys[j], key, gkey], [tmpkeys[j]])
        k.tt('pool', ht[:, sl], ht[:, sl], tmp2[j], ALU.add, [tmpkeys[j], hkey], [hkey])


def build_C1(NTOK, glu, k=None, ob_fm=False):
    k = k or K()
    NT = NTOK // 128
    oa = k.din("oa", [NTOK, 512])
    if ob_fm:
        obT = k.din("obT", [512, NTOK])
    else:
        ob = k.din("ob", [NTOK, 512])
    hin = k.din("hin", [NTOK, D])
    wout = k.din("wout", [D, D])
    g1 = k.din("g1", [D])
    ident_d = k.din("ident", [128, 128])
    if glu:
        wglu = k.din("wglu", [512, 512])
        bglu = k.din("bglu", [512])
    hout = k.dout("hout", [NTOK, D])
    k.consts(ident_d)
    g1bc = k.bcast_row("g1bc", g1, D)
    Wout = k.load_weight("Wout", wout, KC, D, stage_cols=1024)
    if glu:
        Wglu = k.load_weight("Wglu", wglu, 4, 512)
        bgbc = k.bcast_row("bgbc", bglu, 512)

    def ring(nm, shape, n, dt=F32):
        return [k.sb(f"{nm}{j}", shape, dt) for j in range(n)]
    oc = ring("oc", [128, D], 10 if glu else 4)
    ocb = ring("ocb", [128, D], 3, BF16)
    oT = ring("oT", [128, KC, 128], 3, BF16)
    ht = ring("ht", [128, D], 4)
    mix = ring("mix", [128, D], 5)
    tmp = ring("tmp", [128, D], 3)
    ss2 = ring("ss2", [128, 2], 4)
    rstd = ring("rstd", [128, 1], 5)
    junk = k.sb("junk", [128, D], BF16)
    if ob_fm:
        obt = ring("obt", [128, 4, 128], 4)
    if glu:
        yb = ring("yb", [128, 512], 3, BF16)
        yT = ring("yT", [128, 4, 128], 3, BF16)
        t1 = ring("t1", [128, 512], 9)
        zs = ring("zs", [128, 512], 4)
        psTg = k.ps("psTg", [128, D], BF16)
        psG = k.ps("psG", [128, 512])
    psTm = [k.ps(f"psTm{j}", [128, D], BF16) for j in range(2)]
    psM = [k.ps(f"psM{j}", [128, 512]) for j in range(4)]

    def tile(i):
        rows = slice(i * 128, (i + 1) * 128)
        def T(lst, nm):
            j = i % len(lst)
            return lst[j], f'{nm}{j}'
        oc_, koc = T(oc, 'oc'); ocb_, kocb = T(ocb, 'ocb'); oT_, koT = T(oT, 'oT'); ht_, kht = T(ht, 'ht')
        mix_, kmix = T(mix, 'mix'); tmp_, ktmp = T(tmp, 'tmp'); ss_, kss = T(ss2, 'ss2'); rs_, krs = T(rstd, 'rstd')
        pm = [psM[2 * (i % 2)], psM[2 * (i % 2) + 1]]
        kpm = [f'psM{2 * (i % 2)}', f'psM{2 * (i % 2) + 1}']
        ptm, kptm = psTm[i % 2], f'psTm{i % 2}'
        kA, kB = koc + 'A', koc + 'B'
        k.dma('sp', oc_[:, 0:512], oa[rows, :], w=[kA])
        if ob_fm:
            obt_, kobt = T(obt, 'obt')
            k.dma('sp', obt_[:], obT[:, rows].rearrange("(a p) t -> p a t", p=128), w=[kobt])
        else:
            k.dma('sp', oc_[:, 512:1024], ob[rows, :], w=[kB])
        yield
        if glu:
            y = oc_[:, 512:1024]
            yb_, kyb = T(yb, 'yb'); yT_, kyT = T(yT, 'yT'); t1_, kt1 = T(t1, 't1'); zs_, kzs = T(zs, 'zs')
            k.cp('dve', yb_[:], y, [kB], [kyb])
            k.act(t1_[:], y, AF.Square, [kB], [kt1])
            k.act(t1_[:], t1_[:], AF.Copy, [kt1], [kt1], scale=0.044715, bias=1.0)
            yield
            for kc in range(4):
                k.tr(psTg[:, kc * 128:(kc + 1) * 128], yb_[:, kc * 128:(kc + 1) * 128], k.identb[:], [kyb], ['psTg'])
            k.tt('pool', t1_[:], t1_[:], y, ALU.mult, [kt1, kB], [kt1])
            yield
            k.cp('act', yT_[:], psTg[:, 0:512].rearrange("p (k t) -> p k t", k=4), ['psTg'], [kyT])
            k.act(t1_[:], t1_[:], AF.Sigmoid, [kt1], [kt1], scale=GELU_C)
            yield
            for kc in range(4):
                k.mm(psG[:], yT_[:, kc, :], Wglu[:, kc, :], kc == 0, kc == 3, [kyT, f'Wglu{kc}'], ['psG'])
            yield
            k.tt('dve', zs_[:], psG[:], bgbc[:], ALU.add, ['psG', 'bgbc'], [kzs])
            yield
            k.act(zs_[:], zs_[:], AF.Sigmoid, [kzs], [kzs])
            yield
            k.tt('dve', zs_[:], t1_[:], zs_[:], ALU.mult, [kt1, kzs], [kzs])
            k.tt('dve', y, y, zs_[:], ALU.mult, [kB, kzs], [kB])
        if ob_fm:
            k.cp('dve', ocb_[:, 0:512], oc_[:, 0:512], [kA], [kocb])
            k.cp('pool', oT_[:, 4:8, :], obt_[:], [kobt], [koT + 'b'])
        else:
            k.cp('dve', ocb_[:], oc_[:], [kA, kB], [kocb])
        yield
        nk = 4 if ob_fm else KC
        for kc in range(nk):
            k.tr(ptm[:, kc * 128:(kc + 1) * 128], ocb_[:, kc * 128:(kc + 1) * 128], k.identb[:], [kocb], [kptm])
        yield
        k.cp('act', oT_[:, 0:nk, :], ptm[:, 0:nk * 128].rearrange("p (k t) -> p k t", k=nk), [kptm], [koT])
        yield
        for cg in range(2):
            for kc in range(KC):
                ok_ = (koT + 'b') if (ob_fm and kc >= 4) else koT
                k.mm(pm[cg][:], oT_[:, kc, :], Wout[:, kc, cg * 512:(cg + 1) * 512], kc == 0, kc == KC - 1,
                     [ok_, f'Wout{kc}'], [kpm[cg]])
        yield
        for j in range(2):
            k.act(junk[:, 0:512], pm[j][:], AF.Square, [kpm[j]], ['junk', kss], accum_out=ss_[:, j:j + 1])
        for j in range(2):
            k.cp('act', mix_[:, j * 512:(j + 1) * 512], pm[j][:], [kpm[j]], [kmix])
        k.dma('sp', ht_[:], hin[rows, :], w=[kht])
        yield
        k.tt('dve', ss_[:, 0:1], ss_[:, 0:1], ss_[:, 1:2], ALU.add, [kss], [kss])
        k.ts('dve', rs_[:], ss_[:, 0:1], 1.0 / D, EPS, ALU.mult, ALU.add, [kss], [krs])
        yield
        k.act(rs_[:], rs_[:], AF.Sqrt, [krs], [krs])
        yield
        k.recip(rs_[:], rs_[:], [krs], [krs])
        k.stt(tmp_[:], mix_[:], rs_[:], g1bc[:], ALU.mult, ALU.mult, [kmix, krs, 'g1bc'], [ktmp])
        yield
        k.tt('pool', ht_[:], ht_[:], tmp_[:], ALU.add, [kht, ktmp], [kht])
        k.dma('pool', hout[rows, :], ht_[:], r=[kht], final=True)

    pipeline(tile, NT)
    return k.finish()


def build_C3(NTOK, k=None):
    k = k or K()
    NB = NTOK // 512
    DFF = 4096
    FC = DFF // 128
    hin = k.din("hin", [NTOK, D])
    w1 = k.din("w1", [D, DFF])
    w2 = k.din("w2", [DFF, D])
    g4 = k.din("g4", [D])
    g5 = k.din("g5", [D])
    ident_d = k.din("ident", [128, 128])
    hout = k.dout("hout", [NTOK, D])
    k.consts(ident_d)
    g4c = k.gain_cols("g4c", g4)
    g5bc = k.bcast_row("g5bc", g5, D)
    W1 = k.load_weight("W1", w1, KC, DFF, gcol=g4c, gkey='g4c', stage_cols=512)
    W2 = k.load_weight("W2", w2, FC, D, stage_cols=512)
    ht = [k.sb(f"ht{i}", [128, D]) for i in range(4)]
    xn = [k.sb(f"xn{i}", [128, D], BF16) for i in range(2)]
    xT = k.sb("xT", [128, KC, 512], BF16)
    AT = k.sb("AT", [128, FC, 512], BF16)
    sq = [k.sb(f"sq{i}", [128, 512]) for i in range(2)]
    junk = k.sb("junk", [128, D], BF16)
    ss = [k.sb(f"ss{i}", [128, 1]) for i in range(2)]
    ss2 = [k.sb(f"ss2{i}", [128, 2]) for i in range(2)]
    rstd = [k.sb(f"rstd{i}", [128, 1]) for i in range(2)]
    rstd2 = [k.sb(f"rstdb{i}", [128, 1]) for i in range(2)]
    psT = k.ps("psT", [128, D], BF16)
    psU = [k.ps(f"psU{i}", [128, 512]) for i in range(3)]
    psD = [k.ps(f"psD{i}", [128, 512]) for i in range(4)]
    nu = 0
    for blk in range(NB):
        for tt in range(4):
            i = blk * 4 + tt
            b = i % 2
            rows = slice(i * 128, (i + 1) * 128)
            k.dma('sp', ht[tt][:], hin[rows, :], w=[f'ht{tt}'])
            norm_T(k, ht[tt][:], f'ht{tt}', xn[b][:], f'xn{b}', xT[:, :, tt * 128:(tt + 1) * 128], 'xT', psT[:], 'psT',
                   ss[b][:], rstd[b][:], junk[:], f'n{b}')
        for fc in range(FC):
            pu = nu % 3
            nu += 1
            for kc in range(KC):
                k.mm(psU[pu][:], W1[:, kc, fc * 128:(fc + 1) * 128], xT[:, kc, :], kc == 0, kc == KC - 1,
                     [f'W1{kc}', 'xT'], [f'psU{pu}'])
            sb_ = fc % 2
            k.act(sq[sb_][:], psU[pu][:], AF.Square, [f'psU{pu}'], [f'sq{sb_}'])
            k.stt(AT[:, fc, :], psU[pu][:], 0.0, sq[sb_][:], ALU.is_gt, ALU.mult, [f'psU{pu}', f'sq{sb_}'], ['AT'])
        for tt in range(4):
            i = blk * 4 + tt
            b = i % 2
            rows = slice(i * 128, (i + 1) * 128)
            for cg in range(2):
                pd = 2 * b + cg
                for fc in range(FC):
                    k.mm(psD[pd][:], AT[:, fc, tt * 128:(tt + 1) * 128], W2[:, fc, cg * 512:(cg + 1) * 512],
                         fc == 0, fc == FC - 1, ['AT', f'W2{fc}'], [f'psD{pd}'])
            post_norm_res(k, [psD[2 * b][:], psD[2 * b + 1][:]], [f'psD{2 * b}', f'psD{2 * b + 1}'], ht[tt], f'ht{tt}',
                          g5bc, 'g5bc', [sq[0][:], sq[1][:]], ['sq0', 'sq1'], ss2[b], rstd2[b][:], junk, f'pn{b}')
            k.dma('pool', hout[rows, :], ht[tt][:], r=[f'ht{tt}'], final=True)
    return k.finish()


def build_C2(NTOK, k=None):
    k = k or K()
    NB = NTOK // 512
    MEM = 256
    hin = k.din("hin", [NTOK, D])
    mem = k.din("mem", [MEM, D])
    wq = k.din("wq", [D, D])
    wk = k.din("wk", [D, D])
    wv = k.din("wv", [D, D])
    wo = k.din("wo", [D, D])
    g2 = k.din("g2", [D])
    g3 = k.din("g3", [D])
    g6 = k.din("g6", [D])
    ident_d = k.din("ident", [128, 128])
    hout = k.dout("hout", [NTOK, D])
    k.consts(ident_d)
    g2c = k.gain_cols("g2c", g2)
    g6c = k.gain_cols("g6c", g6)
    g3bc = k.bcast_row("g3bc", g3, D)
    Wk = k.load_weight("Wk", wk, KC, D, gcol=g6c, gkey='g6c', stage_cols=1024)
    Wv = k.load_weight("Wv", wv, KC, D, gcol=g6c, gkey='g6c', stage_cols=1024)
    Wq = k.load_weight("Wq", wq, KC, D, gcol=g2c, gkey='g2c', stage_cols=1024)
    Wo = k.load_weight("Wo", wo, KC, D, stage_cols=1024)
    ht = [k.sb(f"ht{i}", [128, D]) for i in range(8)]
    xn = [k.sb(f"xn{i}", [128, D], BF16) for i in range(2)]
    xT = [k.sb(f"xT{i}", [128, KC, 512], BF16) for i in range(2)]
    memT = k.sb("memT", [128, KC, MEM], BF16)
    KT = k.sb("KT", [128, KC, MEM], BF16)
    V = k.sb("V", [128, 2, D], BF16)
    QT = [k.sb(f"QT{i}", [128, KC, 512], BF16) for i in range(2)]
    Pm = [k.sb(f"Pm{i}", [128, 4, MEM], BF16) for i in range(3)]
    Pn = [k.sb(f"Pn{i}", [128, 4, MEM], BF16) for i in range(3)]
    PT = [k.sb(f"PT{i}", [128, 8, 128], BF16) for i in range(3)]
    OT = [k.sb(f"OT{i}", [128, KC, 128], BF16) for i in range(3)]
    tmp = [k.sb(f"tmp{i}", [128, 512]) for i in range(2)]
    junk = k.sb("junk", [128, D], BF16)
    ss = [k.sb(f"ss{i}", [128, 1]) for i in range(2)]
    ss2 = [k.sb(f"ss2{i}", [128, 2]) for i in range(2)]
    rstd = [k.sb(f"rstd{i}", [128, 1]) for i in range(2)]
    rstd2 = [k.sb(f"rstdb{i}", [128, 1]) for i in range(2)]
    mx = [k.sb(f"mx{i}", [128, 4]) for i in range(3)]
    sm = [k.sb(f"sm{i}", [128, 4]) for i in range(3)]
    psT = k.ps("psT", [128, D], BF16)
    psA = k.ps("psA", [128, 1024])
    psS = k.ps("psS", [128, 1024])
    psX = k.ps("psX", [128, 1024])
    for mt in range(2):
        k.dma('sp', ht[mt][:], mem[mt * 128:(mt + 1) * 128, :], w=[f'ht{mt}'])
        norm_T(k, ht[mt][:], f'ht{mt}', xn[mt][:], f'xn{mt}', memT[:, :, mt * 128:(mt + 1) * 128], 'memT', psT[:], 'psT',
               ss[mt][:], rstd[mt][:], junk[:], f'n{mt}')
    for cc in range(KC):
        pa = cc % 2
        for kc in range(KC):
            k.mm(psA[:, pa * 512:pa * 512 + MEM], Wk[:, kc, cc * 128:(cc + 1) * 128], memT[:, kc, :], kc == 0, kc == KC - 1,
                 [f'Wk{kc}', 'memT'], [f'psA{pa}'])
        k.cp('act' if cc % 2 else 'dve', KT[:, cc, :], psA[:, pa * 512:pa * 512 + MEM], [f'psA{pa}'], [f'KT{cc}'])
    for mt in range(2):
        for cg in range(2):
            for kc in range(KC):
                k.mm(psX[:, cg * 512:(cg + 1) * 512], memT[:, kc, mt * 128:(mt + 1) * 128], Wv[:, kc, cg * 512:(cg + 1) * 512],
                     kc == 0, kc == KC - 1, ['memT', f'Wv{kc}'], [f'psX{cg}'])
            k.cp('act' if cg else 'dve', V[:, mt, cg * 512:(cg + 1) * 512], psX[:, cg * 512:(cg + 1) * 512], [f'psX{cg}'], [f'V{mt}{cg}'])
    def tile(i):
        blk, tt = divmod(i, 4)
        xb = blk % 2
        b = i % 3
        rows = slice(i * 128, (i + 1) * 128)
        tsl = slice(tt * 128, (tt + 1) * 128)
        hb = xb * 4 + tt
        if tt == 0:
            for t2_ in range(4):
                i2 = blk * 4 + t2_
                b2 = i2 % 2
                hb2 = xb * 4 + t2_
                k.dma('sp', ht[hb2][:], hin[i2 * 128:(i2 + 1) * 128, :], w=[f'ht{hb2}'])
                norm_T(k, ht[hb2][:], f'ht{hb2}', xn[b2][:], f'xn{b2}', xT[xb][:, :, t2_ * 128:(t2_ + 1) * 128], f'xT{xb}', psT[:], 'psT',
                       ss[b2][:], rstd[b2][:], junk[:], f'n{b2}')
            for cc in range(KC):
                pa = cc % 2
                for kc in range(KC):
                    k.mm(psA[:, pa * 512:(pa + 1) * 512], Wq[:, kc, cc * 128:(cc + 1) * 128], xT[xb][:, kc, :], kc == 0, kc == KC - 1,
                         [f'Wq{kc}', f'xT{xb}'], [f'psA{pa}'])
                k.cp('act' if cc % 2 else 'dve', QT[xb][:, cc, :], psA[:, pa * 512:(pa + 1) * 512], [f'psA{pa}'], [f'QT{xb}{cc}'])
        for h in range(4):
            sb_ = h // 2
            for j in range(2):
                cc = 2 * h + j
                k.mm(psS[:, h * MEM:(h + 1) * MEM], QT[xb][:, cc, tsl], KT[:, cc, :], j == 0, j == 1,
                     [f'QT{xb}{cc}', f'KT{cc}'], [f'psS{sb_}'])
        k.P.op('dve', lambda e, b=b: e.tensor_reduce(out=mx[b][:], in_=psS[:].rearrange("p (h m) -> p h m", h=4),
                                                    axis=AX.X, op=ALU.max),
               reads=['psS0', 'psS1'], writes=[f'mx{b}'])
        k.ts('dve', mx[b][:], mx[b][:], -1.0 / 16.0, None, ALU.mult, None, [f'mx{b}'], [f'mx{b}'])
        for h in range(4):
            k.act(Pm[b][:, h, :], psS[:, h * MEM:(h + 1) * MEM], AF.Exp, [f'psS{h // 2}', f'mx{b}'], [f'Pm{b}', f'sm{b}'],
                  scale=1.0 / 16.0, bias=mx[b][:, h:h + 1], accum_out=sm[b][:, h:h + 1])
        k.recip(sm[b][:], sm[b][:], [f'sm{b}'], [f'sm{b}'])
        k.tt('dve', Pn[b][:], Pm[b][:], sm[b][:].unsqueeze(2).broadcast_to([128, 4, MEM]), ALU.mult,
             [f'Pm{b}', f'sm{b}'], [f'Pn{b}'])
        yield
        for h in range(4):
            for mt in range(2):
                k.tr(psT[:, (h * 2 + mt) * 128:(h * 2 + mt + 1) * 128], Pn[b][:, h, mt * 128:(mt + 1) * 128], k.identb[:],
                     [f'Pn{b}'], ['psT'])
        k.cp('act', PT[b][:], psT[:].rearrange("p (k t) -> p k t", k=8), ['psT'], [f'PT{b}'])
        for cc in range(KC):
            h = cc // 2
            pa = cc // 4
            for mt in range(2):
                k.mm(psA[:, cc * 128:(cc + 1) * 128], V[:, mt, cc * 128:(cc + 1) * 128], PT[b][:, h * 2 + mt, :],
                     mt == 0, mt == 1, [f'V{mt}{cc // 4}', f'PT{b}'], [f'psA{pa}'])
        k.cp('dve', OT[b][:, 0:4, :], psA[:, 0:512].rearrange("p (k t) -> p k t", k=4), ['psA0'], [f'OT{b}_0'])
        k.cp('act', OT[b][:, 4:8, :], psA[:, 512:1024].rearrange("p (k t) -> p k t", k=4), ['psA1'], [f'OT{b}_1'])
        yield
        for cg in range(2):
            for cc in range(KC):
                k.mm(psX[:, cg * 512:(cg + 1) * 512], OT[b][:, cc, :], Wo[:, cc, cg * 512:(cg + 1) * 512],
                     cc == 0, cc == KC - 1, [f'OT{b}_{cc // 4}', f'Wo{cc}'], [f'psX{cg}'])
        post_norm_res(k, [psX[:, 0:512], psX[:, 512:1024]], ['psX0', 'psX1'], ht[hb], f'ht{hb}',
                      g3bc, 'g3bc', [tmp[0][:], tmp[1][:]], ['tmp0', 'tmp1'], ss2[b % 2], rstd2[b % 2][:], junk, f'pn{b % 2}')
        k.dma('pool', hout[rows, :], ht[hb][:], r=[f'ht{hb}'], final=True)

    pipeline(tile, NTOK // 128)
    return k.finish()


def build_A2(NTOK, NC, fm, NF, k=None):
    k = k or K()
    NB = NTOK // 512
    x = k.din("x", [NTOK, D])
    gain = k.din("gain", [D])
    W = k.din("W", [D, NC])
    ident_d = k.din("ident", [128, 128])
    out = k.dout("out", [NTOK, NC])
    outT = k.dout("outT", [NF, NTOK])
    k.consts(ident_d)
    gc = k.gain_cols("gc", gain)
    Wb = k.load_weight("Wb", W, KC, NC, gcol=gc, gkey='gc', stage_cols=1408)
    cgs = [(c0, min(512, NC - c0)) for c0 in range(0, NC, 512)]
    xt = [k.sb(f"xt{i}", [128, D]) for i in range(2)]
    xn = [k.sb(f"xn{i}", [128, D], BF16) for i in range(2)]
    xT = [k.sb(f"xT{i}", [128, KC, 512], BF16) for i in range(2)]
    ot = [k.sb(f"ot{i}", [128, NC]) for i in range(2)]
    ft = [k.sb(f"ft{i}", [128, 512]) for i in range(2)]
    junk = k.sb("junk", [128, D], BF16)
    ss = [k.sb(f"ss{i}", [128, 1]) for i in range(2)]
    rstd = [k.sb(f"rstd{i}", [128, 1]) for i in range(2)]
    psT = k.ps("psT", [128, D], BF16)
    psO = [k.ps(f"psO{i}", [128, 512]) for i in range(4)]
    psF = [k.ps(f"psF{i}", [128, 512]) for i in range(2)]
    no = 0
    nf = 0
    for blk in range(NB):
        xb = blk % 2
        for tt in range(4):
            i = blk * 4 + tt
            b = i % 2
            k.dma('sp', xt[b][:], x[i * 128:(i + 1) * 128, :], w=[f'xt{b}'])
            norm_T(k, xt[b][:], f'xt{b}', xn[b][:], f'xn{b}', xT[xb][:, :, tt * 128:(tt + 1) * 128], f'xT{xb}', psT[:], 'psT',
                   ss[b][:], rstd[b][:], junk[:], f'n{b}')
        for tt in range(4):
            i = blk * 4 + tt
            b = i % 2
            for ci, (c0, cw) in enumerate(cgs):
                pb = no % 4
                no += 1
                for kc in range(KC):
                    k.mm(psO[pb][:, 0:cw], xT[xb][:, kc, tt * 128:(tt + 1) * 128], Wb[:, kc, c0:c0 + cw], kc == 0, kc == KC - 1,
                         [f'xT{xb}', f'Wb{kc}'], [f'psO{pb}'])
                k.cp('dve' if pb % 2 == 0 else 'act', ot[b][:, c0:c0 + cw], psO[pb][:, 0:cw], [f'psO{pb}'], [f'ot{b}_{pb % 2}'])
            k.dma('pool', out[i * 128:(i + 1) * 128, :], ot[b][:], r=[f'ot{b}_0', f'ot{b}_1'], final=True)
        for (c0, cw, r0) in fm:
            pf = nf % 2
            nf += 1
            for kc in range(KC):
                k.mm(psF[pf][0:cw, :], Wb[:, kc, c0:c0 + cw], xT[xb][:, kc, :], kc == 0, kc == KC - 1,
                     [f'Wb{kc}', f'xT{xb}'], [f'psF{pf}'])
            k.cp('dve' if pf == 0 else 'act', ft[pf][0:cw, :], psF[pf][0:cw, :], [f'psF{pf}'], [f'ft{pf}'])
            k.dma('pool', outT[r0:r0 + cw, blk * 512:(blk + 1) * 512], ft[pf][0:cw, :], r=[f'ft{pf}'], final=True)
    return k.finish()


def gen_GLA(L, k):
    NT = L // 128
    qT = k.din("qT", [128, L])
    kT = k.din("kT", [128, L])
    ktok = k.din("ktok", [L, 128])
    v = k.din("v", [L, 256])
    gate = k.din("gate", [L, 256])
    dlrT = k.din("dlrT", [16, L])
    w2 = k.din("w2", [16, 128])
    bdec = k.din("bdec", [1, 128])
    gn = k.din("gn", [256])
    triu_d = k.din("triu", [128, 128])
    trigt_d = k.din("trigt", [128, 128])
    oa = k.dout("oa", [L, 256])

    triu = k.sb("triu_s", [128, 128])
    trigt = k.sb("trigt_s", [128, 128])
    k.dma('sp', triu[:], triu_d, w=['triu'])
    k.dma('sp', trigt[:], trigt_d, w=['trigt'])
    w2s = k.sb("w2s", [16, 128])
    k.dma('sp', w2s[:], w2, w=['w2s'])
    bds = k.sb("bds", [1, 128])
    k.dma('sp', bds[:], bdec, w=['bds'])
    ones1 = k.sb("ones1", [1, 128])
    k.memset('dve', ones1[:], 1.0, ['ones1'])
    gnbc = k.bcast_row("gnbc", gn, 256)
    S = k.sb("S", [128, 128], mybir.dt.float32r)
    zS = k.sb("zS", [128, 128])
    k.memset('dve', zS[:], 0.0, ['zS'])
    k.cp('dve', S[:], zS[:], ['zS'], ['S'])
    rm = k.sb("rm", [128, 2])
    k.memset('dve', rm[:], 0.0, ['rm'])
    k.memset('dve', rm[0:64, 0:1], 0.125, ['rm'])
    k.memset('dve', rm[64:128, 1:2], 0.125, ['rm'])

    def ring(nm, shape, n, dt=F32):
        return [k.sb(f"{nm}{j}", shape, dt) for j in range(n)]
    FR_ = mybir.dt.float32r
    triur = k.sb("triur", [128, 128], FR_)
    trigtr = k.sb("trigtr", [128, 128], FR_)
    k.cp('dve', triur[:], triu[:], ['triu'], ['triur'])
    k.cp('dve', trigtr[:], trigt[:], ['trigt'], ['trigtr'])
    vr = ring("vr", [128, 256], 10, FR_)
    qTt, kTt, kt, gt = ring("qTt", [128, 128], 8), ring("kTt", [128, 128], 8), ring("kt", [128, 128], 8), ring("gt", [128, 256], 8)
    vt = ring("vt", [128, 256], 11)
    dt_ = ring("dt", [16, 128], 3)
    la = ring("la", [128, 128], 4, mybir.dt.float32r)
    sg = ring("sg", [128, 256], 16)
    EqT, EkT, Eks = ring("EqT", [128, 128], 7), ring("EkT", [128, 128], 3), ring("Eks", [128, 128], 3)
    qin, kin, kst = ring("qin", [128, 2, 128], 5, mybir.dt.float32r), ring("kin", [128, 128], 3, mybir.dt.float32r), ring("kst", [128, 128], 5, mybir.dt.float32r)
    sc0, sc1 = ring("sc0_", [128, 128], 3, mybir.dt.float32r), ring("sc1_", [128, 128], 3, mybir.dt.float32r)
    osr = ring("osr", [128, 256], 6)
    osb = ring("osb", [128, 256], 3)
    ss, rs = ring("ss", [128, 2], 4), ring("rs", [128, 2], 5)
    ot = ring("ot", [128, 256], 3)
    junk = k.sb("junk", [128, 128])
    psZ = [k.ps(f"psZ{j}", [128, 512]) for j in range(2)]
    psA = [k.ps(f"psA{j}", [128, 512]) for j in range(2)]
    psB = [k.ps(f"psB{j}", [128, 512]) for j in range(2)]
    psC = [k.ps(f"psC{j}", [128, 512]) for j in range(2)]

    def tile(i):
        rows = slice(i * 128, (i + 1) * 128)
        R = lambda lst: (lst[i % len(lst)], f'{lst[0].name if hasattr(lst[0], "name") else id(lst)}_{i % len(lst)}')
        def T(lst, nm):
            j = i % len(lst)
            return lst[j], f'{nm}{j}'
        q_, kq = T(qTt, 'qTt'); kT_, kkT = T(kTt, 'kTt'); kt_, kkt = T(kt, 'kt'); v_, kv = T(vt, 'vt'); g_, kg = T(gt, 'gt')
        d_, kd = T(dt_, 'dt'); la_, kla = T(la, 'la'); sg_, ksg = T(sg, 'sg')
        Eq, kEq = T(EqT, 'EqT'); Ek, kEk = T(EkT, 'EkT'); Es, kEs = T(Eks, 'Eks')
        qi, kqi = T(qin, 'qin'); ki, kki = T(kin, 'kin'); ks, kks = T(kst, 'kst')
        scs = [T(sc0, 'sc0_'), T(sc1, 'sc1_')]
        orw, korw = T(osr, 'osr'); ob_, kob = T(osb, 'osb'); ss_, kss = T(ss, 'ss'); rs_, krs = T(rs, 'rs'); ot_, kot = T(ot, 'ot')
        pz, kpz = psZ[i % 2], f'psZ{i % 2}'
        pa, kpa = psA[i % 2], f'psA{i % 2}'
        pb, kpb = psB[i % 2], f'psB{i % 2}'
        pc, kpc = psC[i % 2], f'psC{i % 2}'
        k.dma('sp', q_[:], qT[:, rows], w=[kq])
        k.dma('sp', kT_[:], kT[:, rows], w=[kkT])
        k.dma('sp', kt_[:], ktok[rows, :], w=[kkt])
        k.dma('sp', v_[:], v[rows, :], w=[kv])
        k.dma('sp', g_[:], gate[rows, :], w=[kg])
        k.dma('sp', d_[:], dlrT[:, rows], w=[kd])
        yield
        k.mm(pz[:, 0:128], d_[:], w2s[:], True, False, [kd, 'w2s'], [kpz])
        k.mm(pz[:, 0:128], ones1[:], bds[:], False, True, ['ones1', 'bds'], [kpz])
        yield
        k.act(la_[:], pz[:, 0:128], AF.Exp, [kpz], [kla], scale=-1.0)
        k.act(la_[:], la_[:].bitcast(F32), AF.Ln, [kla], [kla], bias=1.0)
        k.act(sg_[:], g_[:], AF.Exp, [kg], [ksg], scale=-1.0)
        vr_, kvr = T(vr, 'vr')
        k.cp('act', vr_[:], v_[:], [kv], [kvr])
        yield
        k.ts('dve', la_[:], la_[:].bitcast(F32), -1.0 / 16.0, None, ALU.mult, None, [kla], [kla])
        k.ts('dve', sg_[:], sg_[:], 1.0, None, ALU.add, None, [ksg], [ksg])
        k.recip(sg_[:], sg_[:], [ksg], [ksg])
        yield
        k.mm(pa[:, 0:128], la_[:], triur[:], True, True, [kla, 'triur'], [kpa])
        k.mm(pa[:, 128:256], trigtr[:], la_[:], True, True, [kla, 'trigtr'], [kpa])
        yield
        k.act(Eq[:], pa[:, 0:128], AF.Exp, [kpa], [kEq])
        k.act(Ek[:], pa[:, 0:128], AF.Exp, [kpa], [kEk], scale=-1.0)
        k.act(Es[:], pa[:, 128:256], AF.Exp, [kpa], [kEs])
        yield
        for h in range(2):
            k.stt(qi[:, h, :], q_[:], rm[:, h:h + 1], Eq[:], ALU.mult, ALU.mult, [kq, kEq, 'rm'], [kqi])
        k.tt('pool', ki[:], kT_[:], Ek[:], ALU.mult, [kkT, kEk], [kki])
        k.tt('pool', ks[:], kt_[:], Es[:], ALU.mult, [kkt, kEs], [kks])
        k.tt('pool', sg_[:], sg_[:], g_[:], ALU.mult, [ksg, kg], [ksg])
        yield
        for h in range(2):
            hp = slice(h * 64, (h + 1) * 64)
            k.mm(pb[:, h * 128:(h + 1) * 128], ki[:], qi[:, h, :], True, True, [kki, kqi], [kpb])
        yield
        for h in range(2):
            k.tt('dve', scs[h][0][:], pb[:, h * 128:(h + 1) * 128], triu[:], ALU.mult, [kpb, 'triu'], [scs[h][1]])
        yield
        for h in range(2):
            hp = slice(h * 64, (h + 1) * 64)
            k.mm(pc[:, h * 128:(h + 1) * 128], scs[h][0][:], vr_[:, h * 128:(h + 1) * 128], True, False, [scs[h][1], kvr], [kpc])
            k.mm(pc[:, h * 128:(h + 1) * 128], qi[:, h, :], S[:], False, True, [kqi, 'S'], [kpc])
        k.mm(pc[:, 256:512], ks[:], vr_[:], True, True, [kks, kvr], [kpc])
        yield
        for h in range(2):
            hp = slice(h * 64, (h + 1) * 64)
            k.stt(S[hp, :], S[hp, :].bitcast(F32), Eq[hp, 127:128], pc[hp, 256 + h * 128:256 + (h + 1) * 128], ALU.mult, ALU.add,
                  ['S', kEq, kpc], ['S'])
        k.cp('act', orw[:], pc[:, 0:256], [kpc], [korw])
        yield
        for h in range(2):
            k.act(junk[:], orw[:, h * 128:(h + 1) * 128], AF.Square, [korw], ['junk', kss], accum_out=ss_[:, h:h + 1])
        yield
        k.ts('dve', rs_[:], ss_[:], 1.0 / 128.0, EPS, ALU.mult, ALU.add, [kss], [krs])
        yield
        k.act(rs_[:], rs_[:], AF.Ln, [krs], [krs])
        k.act(rs_[:], rs_[:], AF.Exp, [krs], [krs], scale=-0.5)
        yield
        for h in range(2):
            hs = slice(h * 128, (h + 1) * 128)
            k.stt(ob_[:, hs], orw[:, hs], rs_[:, h:h + 1], gnbc[:, hs], ALU.mult, ALU.mult, [korw, krs, 'gnbc'], [kob])
        yield
        k.tt('pool', ot_[:], ob_[:], sg_[:], ALU.mult, [kob, ksg], [kot])
        k.dma('pool', oa[rows, :], ot_[:], r=[kot], final=True)

    yield from pipeline_gen(tile, NT)


def build_GLA(L, k=None):
    k = k or K()
    for _ in gen_GLA(L, k):
        pass
    return k.finish()


TWO_PI = 2.0 * math.pi
C1 = 6.28125
C2 = TWO_PI - 6.28125
PI_LO = 3.1415925


def range_sincos(k, x, xkey, shape, s_out, c_out, skey, ckey, pfx):
    if not hasattr(k, 'rr_cache'):
        k.rr_cache = {}
    if pfx not in k.rr_cache:
        k.rr_cache[pfx] = (k.sb(pfx + "kf", shape), k.sb(pfx + "ki", shape, I32), k.sb(pfx + "r", shape), k.sb(pfx + "m", shape))
    kf, ki, r, m = k.rr_cache[pfx]
    a = lambda t: t[:]
    K1, K2, K3, K4 = pfx + 'kf', pfx + 'ki', pfx + 'r', pfx + 'm'
    k.ts('dve', a(kf), x, 1.0 / TWO_PI, None, ALU.mult, None, [xkey], [K1])
    k.cp('dve', a(ki), a(kf), [K1], [K2])
    k.cp('dve', a(kf), a(ki), [K2], [K1])
    k.stt(a(r), a(kf), -C1, x, ALU.mult, ALU.add, [K1, xkey], [K3])
    k.stt(a(r), a(kf), -C2, a(r), ALU.mult, ALU.add, [K1, K3], [K3])
    k.ts('dve', a(m), a(r), math.pi, -TWO_PI, ALU.is_gt, ALU.mult, [K3], [K4])
    k.tt('dve', a(r), a(r), a(m), ALU.add, [K3, K4], [K3])
    k.ts('dve', a(m), a(r), -math.pi, TWO_PI, ALU.is_lt, ALU.mult, [K3], [K4])
    k.tt('dve', a(r), a(r), a(m), ALU.add, [K3, K4], [K3])
    k.ts('dve', a(kf), a(r), PI_LO, -PI_LO, ALU.min, ALU.max, [K3], [K1])
    k.act(s_out, a(kf), AF.Sin, [K1], [skey])
    k.ts('dve', a(r), a(r), math.pi / 2, None, ALU.add, None, [K3], [K3])
    k.ts('dve', a(m), a(r), math.pi, -TWO_PI, ALU.is_gt, ALU.mult, [K3], [K4])
    k.tt('dve', a(r), a(r), a(m), ALU.add, [K3, K4], [K3])
    k.ts('dve', a(kf), a(r), PI_LO, -PI_LO, ALU.min, ALU.max, [K3], [K1])
    k.act(c_out, a(kf), AF.Sin, [K1], [ckey])


def gen_S5(L, k):
    NT = L // 128
    NS = 1024
    uT = k.din("uT", [256, L])
    u = k.din("u", [L, 256])
    lam_re = k.din("lam_re", [NS])
    lam_im = k.din("lam_im", [NS])
    lstep = k.din("lstep", [NS])
    Bre = k.din("Bre", [2, 128, 512])
    Bim = k.din("Bim", [2, 128, 512])
    Cre = k.din("Cre", [8, 128, 32])
    Cim = k.din("Cim", [8, 128, 32])
    dsk = k.din("dsk", [256])
    triu_d = k.din("triu", [128, 128])
    iop_d = k.din("iota_p", [128, 1])
    iof_d = k.din("iota_f", [128, 128])
    y = k.dout("y", [L, 256])

    k.push_scope([("triu_s", [128, 128], F32), ("dbc", [128, 256], F32), ("BBr", [128, 2, 512], mybir.dt.float32r), ("BBi", [128, 2, 512], mybir.dt.float32r),
                  ("Pr", [128, NS], F32), ("Pi", [128, NS], F32), ("Qr", [128, 8, 128], F32), ("Qi", [128, 8, 128], F32),
                  ("L128r", [128, 8], F32), ("L128i", [128, 8], F32), ("Cr", [128, 8, 32], F32), ("nCi", [128, 8, 32], F32),
                  ("car_r", [128, 8], F32), ("car_i", [128, 8], F32), ("ntriu", [128, 128], mybir.dt.float32r), ("nCr", [128, 8, 32], mybir.dt.float32r), ("triur", [128, 128], mybir.dt.float32r), ("Crr", [128, 8, 32], mybir.dt.float32r), ("nCir", [128, 8, 32], mybir.dt.float32r)])
    triu = k.sb("triu_s", [128, 128])
    k.dma('sp', triu[:], triu_d, w=['triu'])
    iop = k.sb("iop", [128, 1])
    k.dma('sp', iop[:], iop_d, w=['iop'])
    negp = k.sb("negp", [128, 1])
    k.ts('dve', negp[:], iop[:], -1.0, None, ALU.mult, None, ['iop'], ['negp'])
    iof = k.sb("iof", [128, 128])
    k.dma('sp', iof[:], iof_d, w=['iof'])
    dbc = k.bcast_row("dbc", dsk, 256)
    R = [128, NS]
    lr = k.bcast_row("lr", lam_re, NS)
    li = k.bcast_row("li", lam_im, NS)
    dl = k.bcast_row("dl", lstep, NS)
    k.ts('dve', lr[:], lr[:], -1e-4, None, ALU.min, None, ['lr'], ['lr'])
    k.act(dl[:], dl[:], AF.Exp, ['dl'], ['dl'])
    a_ = k.sb("a_", R)
    th = k.sb("th", R)
    k.tt('dve', a_[:], lr[:], dl[:], ALU.mult, ['lr', 'dl'], ['a_'])
    k.tt('dve', th[:], li[:], dl[:], ALU.mult, ['li', 'dl'], ['th'])
    sn = k.sb("sn", R)
    cs = k.sb("cs", R)
    range_sincos(k, th[:], 'th', R, sn[:], cs[:], 'sn', 'cs', 'rr_')
    ea = k.sb("ea", R)
    k.act(ea[:], a_[:], AF.Exp, ['a_'], ['ea'])
    nr = k.sb("nr", R)
    ni = k.sb("ni", R)
    k.tt('dve', nr[:], ea[:], cs[:], ALU.mult, ['ea', 'cs'], ['nr'])
    k.ts('dve', nr[:], nr[:], -1.0, None, ALU.add, None, ['nr'], ['nr'])
    k.tt('dve', ni[:], ea[:], sn[:], ALU.mult, ['ea', 'sn'], ['ni'])
    den = k.sb("den", R)
    t0 = k.sb("t0", R)
    k.tt('dve', den[:], lr[:], lr[:], ALU.mult, ['lr'], ['den'])
    k.tt('dve', t0[:], li[:], li[:], ALU.mult, ['li'], ['t0'])
    k.tt('dve', den[:], den[:], t0[:], ALU.add, ['den', 't0'], ['den'])
    k.recip(den[:], den[:], ['den'], ['den'])
    gr = k.sb("gr", R)
    gi = k.sb("gi", R)
    k.tt('dve', gr[:], nr[:], lr[:], ALU.mult, ['nr', 'lr'], ['gr'])
    k.tt('dve', t0[:], ni[:], li[:], ALU.mult, ['ni', 'li'], ['t0'])
    k.tt('dve', gr[:], gr[:], t0[:], ALU.add, ['gr', 't0'], ['gr'])
    k.tt('dve', gr[:], gr[:], den[:], ALU.mult, ['gr', 'den'], ['gr'])
    k.tt('dve', gi[:], ni[:], lr[:], ALU.mult, ['ni', 'lr'], ['gi'])
    k.tt('dve', t0[:], nr[:], li[:], ALU.mult, ['nr', 'li'], ['t0'])
    k.tt('dve', gi[:], gi[:], t0[:], ALU.subtract, ['gi', 't0'], ['gi'])
    k.tt('dve', gi[:], gi[:], den[:], ALU.mult, ['gi', 'den'], ['gi'])
    Br = k.sb("Br", [128, 2, 512])
    Bi = k.sb("Bi", [128, 2, 512])
    BBr = k.sb("BBr", [128, 2, 512])
    BBi = k.sb("BBi", [128, 2, 512])
    for hc in range(2):
        k.dma('sp', Br[:, hc, :], Bre[hc], w=[f'Br{hc}'])
        k.dma('sp', Bi[:, hc, :], Bim[hc], w=[f'Bi{hc}'])
    grv = gr[:].rearrange("p (h n) -> p h n", h=2)
    giv = gi[:].rearrange("p (h n) -> p h n", h=2)
    t0v = t0[:].rearrange("p (h n) -> p h n", h=2)
    BK = ['Br0', 'Br1', 'Bi0', 'Bi1']
    k.tt('dve', BBr[:], grv, Br[:], ALU.mult, ['gr'] + BK, ['BBr'])
    k.tt('dve', t0v, giv, Bi[:], ALU.mult, ['gi'] + BK, ['t0'])
    k.tt('dve', BBr[:], BBr[:].bitcast(F32), t0v, ALU.subtract, ['BBr', 't0'], ['BBr'])
    k.tt('dve', BBi[:], grv, Bi[:], ALU.mult, ['gr'] + BK, ['BBi'])
    k.tt('dve', t0v, giv, Br[:], ALU.mult, ['gi'] + BK, ['t0'])
    k.tt('dve', BBi[:], BBi[:].bitcast(F32), t0v, ALU.add, ['BBi', 't0'], ['BBi'])
    ang = k.sb("ang", R)
    k.ts('dve', ang[:], th[:], iop[:, 0:1], None, ALU.mult, None, ['th', 'iop'], ['ang'])
    Pr = k.sb("Pr", R)
    Pi = k.sb("Pi", R)
    range_sincos(k, ang[:], 'ang', R, sn[:], cs[:], 'sn', 'cs', 'rr_')
    k.act(ea[:], a_[:], AF.Exp, ['a_', 'negp'], ['ea'], scale=negp[:, 0:1])
    k.tt('dve', Pr[:], ea[:], cs[:], ALU.mult, ['ea', 'cs'], ['Pr'])
    k.stt(Pi[:], ea[:], -1.0, sn[:], ALU.mult, ALU.mult, ['ea', 'sn'], ['Pi'])
    Cs = [128, 8]
    lrc = k.sb("lrc", Cs)
    lic = k.sb("lic", Cs)
    dlc = k.sb("dlc", Cs)
    cv = lambda d: d.rearrange("(blk p) -> p blk", p=128)
    k.dma('sp', lrc[:], cv(lam_re), w=['lrc'], allow_slow_non_contiguous=True)
    k.dma('sp', lic[:], cv(lam_im), w=['lic'], allow_slow_non_contiguous=True)
    k.dma('sp', dlc[:], cv(lstep), w=['dlc'], allow_slow_non_contiguous=True)
    k.ts('dve', lrc[:], lrc[:], -1e-4, None, ALU.min, None, ['lrc'], ['lrc'])
    k.act(dlc[:], dlc[:], AF.Exp, ['dlc'], ['dlc'])
    ac = k.sb("ac", Cs)
    thc = k.sb("thc", Cs)
    k.tt('dve', ac[:], lrc[:], dlc[:], ALU.mult, ['lrc', 'dlc'], ['ac'])
    k.tt('dve', thc[:], lic[:], dlc[:], ALU.mult, ['lic', 'dlc'], ['thc'])
    Qr = k.sb("Qr", [128, 8, 128])
    Qi = k.sb("Qi", [128, 8, 128])
    angv = ang[:].rearrange("p (b t) -> p b t", b=8)
    eav = ea[:].rearrange("p (b t) -> p b t", b=8)
    for blk in range(8):
        k.ts('dve', angv[:, blk, :], iof[:], thc[:, blk:blk + 1], None, ALU.mult, None, ['iof', 'thc'], ['ang'])
    range_sincos(k, ang[:], 'ang', R, sn[:], cs[:], 'sn', 'cs', 'rr_')
    for blk in range(8):
        k.act(eav[:, blk, :], iof[:], AF.Exp, ['iof', 'ac'], ['ea'], scale=ac[:, blk:blk + 1])
    k.tt('dve', Qr[:].rearrange("p b t -> p (b t)"), ea[:], cs[:], ALU.mult, ['ea', 'cs'], ['Qr'])
    k.tt('dve', Qi[:].rearrange("p b t -> p (b t)"), ea[:], sn[:], ALU.mult, ['ea', 'sn'], ['Qi'])
    a128 = k.sb("a128", Cs)
    s128 = k.sb("s128", Cs)
    c128 = k.sb("c128", Cs)
    L128r = k.sb("L128r", Cs)
    L128i = k.sb("L128i", Cs)
    k.ts('dve', a128[:], thc[:], 128.0, None, ALU.mult, None, ['thc'], ['a128'])
    range_sincos(k, a128[:], 'a128', Cs, s128[:], c128[:], 's128', 'c128', 'rc_')
    k.act(a128[:], ac[:], AF.Exp, ['ac', 's128', 'c128'], ['a128'], scale=128.0)
    k.tt('dve', L128r[:], a128[:], c128[:], ALU.mult, ['a128', 'c128'], ['L128r'])
    k.tt('dve', L128i[:], a128[:], s128[:], ALU.mult, ['a128', 's128'], ['L128i'])
    Cr = k.sb("Cr", [128, 8, 32])
    nCi = k.sb("nCi", [128, 8, 32])
    k.dma('sp', Cr[:], Cre.rearrange("b p c -> p b c"), w=['Cr'])
    k.dma('sp', nCi[:], Cim.rearrange("b p c -> p b c"), w=['nCi'])
    k.ts('dve', nCi[:], nCi[:], -1.0, None, ALU.mult, None, ['nCi'], ['nCi'])
    car_r = k.sb("car_r", Cs)
    car_i = k.sb("car_i", Cs)
    k.memset('dve', car_r[:], 0.0, ['car_r0', 'car_r1'])
    k.memset('dve', car_i[:], 0.0, ['car_i0', 'car_i1'])
    ntriu = k.sb("ntriu", [128, 128])
    k.ts('dve', ntriu[:], triu[:], -1.0, None, ALU.mult, None, ['triu'], ['ntriu'])
    nCr = k.sb("nCr", [128, 8, 32])
    k.ts('dve', nCr[:], Cr[:], -1.0, None, ALU.mult, None, ['Cr'], ['nCr'])
    triur = k.sb("triur", [128, 128])
    k.cp('dve', triur[:], triu[:], ['triu'], ['triur'])
    Crr = k.sb("Crr", [128, 8, 32])
    k.cp('dve', Crr[:], Cr[:], ['Cr'], ['Crr'])
    nCir = k.sb("nCir", [128, 8, 32])
    k.cp('dve', nCir[:], nCi[:], ['nCi'], ['nCir'])
    k.pop_scope()
    if hasattr(k, 'rr_cache'):
        del k.rr_cache
    def ring(nm, shape, n, dt=F32):
        return [k.sb(f"{nm}{j}", shape, dt) for j in range(n)]
    FR_ = mybir.dt.float32r
    uTt = ring("uTt", [128, 128], 3)
    uTr = ring("uTr", [128, 128], 3, FR_)
    ut = ring("ut", [128, 128], 5)
    yo = ring("yo", [128, 128], 9)
    m1, m2, m3, m4 = ring("m1_", [128, 512], 3, FR_), ring("m2_", [128, 512], 3, FR_), ring("m3_", [128, 512], 3, FR_), ring("m4_", [128, 512], 3, FR_)
    Xtr, Xti = ring("Xtr", [128, 512], 3), ring("Xti", [128, 512], 3)
    Gr, Gi = ring("Gr", [128, 4, 128], 3), ring("Gi", [128, 4, 128], 3)
    n1, n2, n3, n4 = ring("n1_", [128, 512], 3, FR_), ring("n2_", [128, 512], 3, FR_), ring("n3_", [128, 512], 3, FR_), ring("n4_", [128, 512], 3, FR_)
    Hr, Hi = ring("Hr", [128, 4, 128], 3), ring("Hi", [128, 4, 128], 3)
    cc1 = [k.sb(f"cc1_{h}", [128, 4]) for h in range(2)]
    cc2 = [k.sb(f"cc2_{h}", [128, 4]) for h in range(2)]
    psXr = k.ps("psXr", [128, 512])
    psXi = k.ps("psXi", [128, 512])
    psGr = k.ps("psGr", [128, 512])
    psGi = k.ps("psGi", [128, 512])
    psY = k.ps("psY", [128, 512])
    fl = lambda t: t[:].rearrange("p b t -> p (b t)")

    def item(j):
        i, hc = divmod(j, 2)
        rows = slice(i * 128, (i + 1) * 128)
        cs_ = slice(hc * 512, (hc + 1) * 512)
        bs = slice(hc * 4, (hc + 1) * 4)
        def T(lst, nm):
            q = j % len(lst)
            return lst[q], f'{nm}{q}'
        uT_, kuT = T(uTt, 'uTt'); uR_, kuR = T(uTr, 'uTr'); ut_, kut = T(ut, 'ut'); yo_, kyo = T(yo, 'yo')
        m1_, km1 = T(m1, 'm1'); m2_, km2 = T(m2, 'm2'); m3_, km3 = T(m3, 'm3'); m4_, km4 = T(m4, 'm4')
        Xr_, kXr = T(Xtr, 'Xtr'); Xi_, kXi = T(Xti, 'Xti'); Gr_, kGr = T(Gr, 'Gr'); Gi_, kGi = T(Gi, 'Gi')
        n1_, kn1 = T(n1, 'n1'); n2_, kn2 = T(n2, 'n2'); n3_, kn3 = T(n3, 'n3'); n4_, kn4 = T(n4, 'n4')
        Hr_, kHr = T(Hr, 'Hr'); Hi_, kHi = T(Hi, 'Hi')
        k.dma('sp', uT_[:], uT[hc * 128:(hc + 1) * 128, rows], w=[kuT])
        k.dma('sp', ut_[:], u[rows, hc * 128:(hc + 1) * 128], w=[kut])
        yield
        k.cp('act', uR_[:], uT_[:], [kuT], [kuR])
        yield
        k.mm(psXr[:], uR_[:], BBr[:, hc, :], True, True, [kuR, 'BBr'], ['psXr'])
        k.mm(psXi[:], uR_[:], BBi[:, hc, :], True, True, [kuR, 'BBi'], ['psXi'])
        yield
        k.tt('dve', m1_[:], psXr[:], Pr[:, cs_], ALU.mult, ['psXr', 'Pr'], [km1])
        k.tt('dve', m3_[:], psXr[:], Pi[:, cs_], ALU.mult, ['psXr', 'Pi'], [km3])
        k.tt('dve', m2_[:], psXi[:], Pi[:, cs_], ALU.mult, ['psXi', 'Pi'], [km2])
        k.tt('dve', m4_[:], psXi[:], Pr[:, cs_], ALU.mult, ['psXi', 'Pr'], [km4])
        yield
        k.tt('pool', yo_[:], ut_[:], dbc[:, hc * 128:(hc + 1) * 128], ALU.mult, [kut, 'dbc'], [kyo])
        yield
        for nb in range(4):
            ns = slice(nb * 128, (nb + 1) * 128)
            k.mm(psGr[:, ns], m1_[:, ns], triur[:], True, False, [km1, 'triur'], ['psGr'])
            k.mm(psGr[:, ns], m2_[:, ns], ntriu[:], False, True, [km2, 'ntriu'], ['psGr'])
            k.mm(psGi[:, ns], m3_[:, ns], triur[:], True, False, [km3, 'triur'], ['psGi'])
            k.mm(psGi[:, ns], m4_[:, ns], triur[:], False, True, [km4, 'triur'], ['psGi'])
        yield
        k.tt('dve', Gr_[:], psGr[:].rearrange("p (b t) -> p b t", b=4),
             car_r[:, bs].unsqueeze(2).broadcast_to([128, 4, 128]), ALU.add, ['psGr', f'car_r{hc}'], [kGr])
        k.tt('dve', Gi_[:], psGi[:].rearrange("p (b t) -> p b t", b=4),
             car_i[:, bs].unsqueeze(2).broadcast_to([128, 4, 128]), ALU.add, ['psGi', f'car_i{hc}'], [kGi])
        gr127 = Gr_[:, :, 127]
        gi127 = Gi_[:, :, 127]
        CK = [f'cc1{hc}', f'cc2{hc}']
        k.tt('dve', cc1[hc][:], L128r[:, bs], gr127, ALU.mult, ['L128r', kGr], [CK[0]])
        k.tt('dve', cc2[hc][:], L128i[:, bs], gi127, ALU.mult, ['L128i', kGi], [CK[1]])
        k.tt('dve', car_r[:, bs], cc1[hc][:], cc2[hc][:], ALU.subtract, CK, [f'car_r{hc}'])
        k.tt('dve', cc1[hc][:], L128r[:, bs], gi127, ALU.mult, ['L128r', kGi], [CK[0]])
        k.tt('dve', cc2[hc][:], L128i[:, bs], gr127, ALU.mult, ['L128i', kGr], [CK[1]])
        k.tt('dve', car_i[:, bs], cc1[hc][:], cc2[hc][:], ALU.add, CK, [f'car_i{hc}'])
        yield
        qr = Qr[:, bs, :].rearrange("p b t -> p (b t)")
        qi = Qi[:, bs, :].rearrange("p b t -> p (b t)")
        k.tt('dve', n1_[:], fl(Gr_), qr, ALU.mult, [kGr, 'Qr'], [kn1])
        k.tt('dve', n2_[:], fl(Gi_), qi, ALU.mult, [kGi, 'Qi'], [kn2])
        k.tt('dve', n3_[:], fl(Gi_), qr, ALU.mult, [kGi, 'Qr'], [kn3])
        k.tt('dve', n4_[:], fl(Gr_), qi, ALU.mult, [kGr, 'Qi'], [kn4])
        yield
        for nb in range(4):
            blk = hc * 4 + nb
            ns = slice(nb * 128, (nb + 1) * 128)
            yo_s = psY[:, blk * 32:(blk + 1) * 32]
            k.mm(yo_s, n1_[:, ns], Crr[:, blk, :], True, False, [kn1, 'Crr'], ['psY'])
            k.mm(yo_s, n2_[:, ns], nCr[:, blk, :], False, False, [kn2, 'nCr'], ['psY'])
            k.mm(yo_s, n3_[:, ns], nCir[:, blk, :], False, False, [kn3, 'nCir'], ['psY'])
            k.mm(yo_s, n4_[:, ns], nCir[:, blk, :], False, True, [kn4, 'nCir'], ['psY'])
        yield
        k.tt('dve', yo_[:], yo_[:], psY[:, hc * 128:(hc + 1) * 128], ALU.add, [kyo, 'psY'], [kyo])
        yield
        k.dma('pool', y[rows, hc * 128:(hc + 1) * 128], yo_[:], r=[kyo], final=True)

    yield from pipeline_gen(item, 2 * NT)


def build_S5(L, k=None):
    k = k or K()
    for _ in gen_S5(L, k):
        pass
    return k.finish()


def s5_host_inputs(s, proj_u, prm):
    gs = slice(16 * s, 16 * s + 16)
    cs = slice(256 * s, 256 * s + 256)
    uc = np.ascontiguousarray(proj_u[:, cs])
    Bre = np.zeros((2, 128, 512), np.float32)
    Bim = np.zeros((2, 128, 512), np.float32)
    Cre = np.zeros((8, 128, 32), np.float32)
    Cim = np.zeros((8, 128, 32), np.float32)
    b_re, b_im = prm['s5_b_re'][gs], prm['s5_b_im'][gs]
    c_re, c_im = prm['s5_c_re'][gs], prm['s5_c_im'][gs]
    for g in range(16):
        hc, gl = g // 8, g % 8
        Bre[hc, gl * 16:(gl + 1) * 16, gl * 64:(gl + 1) * 64] = b_re[g].T
        Bim[hc, gl * 16:(gl + 1) * 16, gl * 64:(gl + 1) * 64] = b_im[g].T
        blk, g2 = g // 2, g % 2
        Cre[blk, g2 * 64:(g2 + 1) * 64, g2 * 16:(g2 + 1) * 16] = c_re[g].T
        Cim[blk, g2 * 64:(g2 + 1) * 64, g2 * 16:(g2 + 1) * 16] = c_im[g].T
    return dict(uT=np.ascontiguousarray(uc.T), u=uc,
                lam_re=np.ascontiguousarray(prm['s5_lambda_re'][gs].reshape(-1)),
                lam_im=np.ascontiguousarray(prm['s5_lambda_im'][gs].reshape(-1)),
                lstep=np.ascontiguousarray(np.repeat(prm['s5_log_step'][gs], 64)),
                Bre=Bre, Bim=Bim, Cre=Cre, Cim=Cim, dsk=np.ascontiguousarray(prm['s5_d'][cs]),
                triu=np.triu(np.ones((128, 128), np.float32)),
                iota_p=np.arange(128, dtype=np.float32).reshape(128, 1),
                iota_f=np.tile(np.arange(128, dtype=np.float32)[None], (128, 1)))


GELU_C = 1.5957691216057308


def gen_LRU(L, k):
    TT = 512
    NCH = L // TT
    xbT = k.din("xbT", [256, L])
    gateT = k.din("gateT", [256, L])
    cw_d = k.din("cw", [128, 2, 4])
    cb_d = k.din("cb", [128, 2])
    Wa_d = k.din("Wa", [2, 128, 128])
    Wx_d = k.din("Wx", [2, 128, 128])
    ba_d = k.din("ba", [128, 2])
    bx_d = k.din("bx", [128, 2])
    lam_d = k.din("lam", [128, 2])
    odT = k.dout("odT", [256, L])
    cw = k.sb("cw_s", [128, 2, 4])
    cb = k.sb("cb_s", [128, 2])
    Wa = k.sb("Wa_s", [128, 2, 128])
    Wx = k.sb("Wx_s", [128, 2, 128])
    ba = k.sb("ba_s", [128, 2])
    bx = k.sb("bx_s", [128, 2])
    c8 = k.sb("c8", [128, 2])
    k.dma('sp', cw[:], cw_d, w=['cw'])
    k.dma('sp', cb[:], cb_d, w=['cb'])
    k.dma('sp', Wa[:], Wa_d.rearrange("b p n -> p b n"), w=['Wa'])
    k.dma('sp', Wx[:], Wx_d.rearrange("b p n -> p b n"), w=['Wx'])
    k.dma('sp', ba[:], ba_d, w=['ba'])
    k.dma('sp', bx[:], bx_d, w=['bx'])
    k.dma('sp', c8[:], lam_d, w=['c8'])
    k.act(c8[:], c8[:], AF.Exp, ['c8'], ['c8'], scale=-1.0)
    k.act(c8[:], c8[:], AF.Ln, ['c8'], ['c8'], bias=1.0)
    k.ts('dve', c8[:], c8[:], -8.0, None, ALU.mult, None, ['c8'], ['c8'])
    hlast = k.sb("hlast", [128, 2])
    k.memset('dve', hlast[:], 0.0, ['hlast0', 'hlast1'])
    xh = [k.sb(f"xh{i}", [128, TT + 3]) for i in range(2)]
    gt = [k.sb(f"gt{i}", [128, TT]) for i in range(2)]
    xc = k.sb("xc", [128, TT])
    r = k.sb("r", [128, TT])
    ig = k.sb("ig", [128, TT])
    a = k.sb("a", [128, TT])
    a2 = k.sb("a2", [128, TT])
    bt = k.sb("bt", [128, TT])
    h = k.sb("h", [128, TT])
    g2 = k.sb("g2", [128, TT])
    ge = k.sb("ge", [128, TT])
    ot = [k.sb(f"ot{i}", [128, TT]) for i in range(2)]
    psR = k.ps("psR", [128, TT])
    psI = k.ps("psI", [128, TT])
    n = 0
    for c in range(NCH):
        for pb in range(2):
            b = n % 2
            n += 1
            prow = slice(pb * 128, (pb + 1) * 128)
            if c == 0:
                k.memset('pool', xh[b][:, 0:3], 0.0, [f'xh{b}h'])
                k.dma('sp', xh[b][:, 3:TT + 3], xbT[prow, 0:TT], w=[f'xh{b}'])
            else:
                k.dma('sp', xh[b][:, 0:TT + 3], xbT[prow, c * TT - 3:(c + 1) * TT], w=[f'xh{b}', f'xh{b}h'])
            k.dma('sp', gt[b][:], gateT[prow, c * TT:(c + 1) * TT], w=[f'gt{b}'])
            xk = [f'xh{b}', f'xh{b}h']
            k.ts('dve', xc[:], xh[b][:, 3:TT + 3], cw[:, pb, 3:4], cb[:, pb:pb + 1], ALU.mult, ALU.add, xk + ['cw', 'cb'], ['xc'])
            for j in (2, 1, 0):
                k.stt(xc[:], xh[b][:, j:j + TT], cw[:, pb, j:j + 1], xc[:], ALU.mult, ALU.add, xk + ['cw', 'xc'], ['xc'])
            k.mm(psR[:], Wa[:, pb, :], xc[:], True, True, ['Wa', 'xc'], ['psR'])
            k.mm(psI[:], Wx[:, pb, :], xc[:], True, True, ['Wx', 'xc'], ['psI'])
            k.act(r[:], psR[:], AF.Sigmoid, ['psR', 'ba'], ['r'], bias=ba[:, pb:pb + 1])
            k.act(ig[:], psI[:], AF.Sigmoid, ['psI', 'bx'], ['ig'], bias=bx[:, pb:pb + 1])
            k.act(a[:], r[:], AF.Exp, ['r', 'c8'], ['a'], scale=c8[:, pb:pb + 1])
            k.act(a2[:], a[:], AF.Square, ['a'], ['a2'])
            k.act(a2[:], a2[:], AF.Sqrt, ['a2'], ['a2'], scale=-1.0, bias=1.0)
            k.tt('dve', bt[:], ig[:], xc[:], ALU.mult, ['ig', 'xc'], ['bt'])
            k.tt('dve', bt[:], bt[:], a2[:], ALU.mult, ['bt', 'a2'], ['bt'])
            k.P.op('dve', lambda e, pb=pb: e.tensor_tensor_scan(out=h[:], data0=a[:], data1=bt[:], initial=hlast[:, pb:pb + 1],
                                                                op0=ALU.mult, op1=ALU.add),
                   reads=['a', 'bt', f'hlast{pb}'], writes=['h'])
            k.cp('dve', hlast[:, pb:pb + 1], h[:, TT - 1:TT], ['h'], [f'hlast{pb}'])
            k.act(g2[:], gt[b][:], AF.Square, [f'gt{b}'], ['g2'])
            k.act(g2[:], g2[:], AF.Copy, ['g2'], ['g2'], scale=0.044715, bias=1.0)
            k.tt('dve', g2[:], g2[:], gt[b][:], ALU.mult, ['g2', f'gt{b}'], ['g2'])
            k.act(g2[:], g2[:], AF.Sigmoid, ['g2'], ['g2'], scale=GELU_C)
            k.tt('dve', ge[:], g2[:], gt[b][:], ALU.mult, ['g2', f'gt{b}'], ['ge'])
            k.tt('dve', ot[b][:], h[:], ge[:], ALU.mult, ['h', 'ge'], [f'ot{b}'])
            k.dma('pool', odT[prow, c * TT:(c + 1) * TT], ot[b][:], r=[f'ot{b}'], final=True)
            yield


def build_LRU(L, k=None):
    k = k or K()
    for _ in gen_LRU(L, k):
        pass
    return k.finish()


def lru_host_inputs(s, xb, gate, prm):
    cs = slice(256 * s, 256 * s + 256)
    col = lambda v: np.ascontiguousarray(v[cs].reshape(2, 128).T)
    Wa = np.zeros((2, 128, 128), np.float32)
    Wx = np.zeros((2, 128, 128), np.float32)
    for pb in range(2):
        for bl in range(2):
            blk = 4 * s + 2 * pb + bl
            Wa[pb, bl * 64:(bl + 1) * 64, bl * 64:(bl + 1) * 64] = prm['lru_w_a'][blk]
            Wx[pb, bl * 64:(bl + 1) * 64, bl * 64:(bl + 1) * 64] = prm['lru_w_x'][blk]
    cw = np.ascontiguousarray(prm['lru_conv_w'][:, cs].reshape(4, 2, 128).transpose(2, 1, 0))
    return dict(xbT=np.ascontiguousarray(xb[:, cs].T), gateT=np.ascontiguousarray(gate[:, cs].T), cw=cw,
                cb=col(prm['lru_conv_b']), Wa=Wa, Wx=Wx, ba=col(prm['lru_b_a']), bx=col(prm['lru_b_x']),
                lam=col(prm['lru_lambda']))


GN_EPS = 64e-5
NLEV = 5


def build_RWKV(L, k=None, NH=4, fr=False, CH=64):
    k = k or K()
    NT = L // 128
    W = NH * 64
    NG = NH // 4
    FR = mybir.dt.float32r if fr else F32
    rd = (lambda ap: ap.bitcast(F32)) if fr else (lambda ap: ap)
    NCK = 128 // CH
    nlev = 5 if CH == 64 else 6
    frc = fr and CH == 128
    FRC = mybir.dt.float32r if frc else F32
    rdc = (lambda ap: ap.bitcast(F32)) if frc else (lambda ap: ap)
    lhc = (lambda ap: ap) if frc else rd
    prkv = [k.din(nm, [L, W]) for nm in ("pr", "pk", "pv")]
    mu1 = k.din("mu1", [3 * W])
    pls = [k.din("plw", [64, L]), k.din("pla", [64, L]), k.din("plg", [128, L])]
    mul = k.din("mul", [128, 3])
    w2 = k.din("w2", [64, W])
    a2 = k.din("a2", [64, W])
    g2 = k.din("g2", [128, W])
    vecs = k.din("vecs", [7, W])
    ident_d = k.din("ident", [128, 128])
    triw_d = k.din("triw", [3, 128, 128])
    mask5_d = k.din("mask5", [128, 640])
    rowm_d = k.din("rowm", [128, 2])
    oc = k.dout("oc", [L, W])

    k.consts(ident_d)
    triw = k.sb("triw_s", [128, 3, 128])
    k.dma('sp', triw[:], triw_d.rearrange("a p n -> p a n"), w=['triw'])
    mask5 = k.sb("mask5_s", [128, 640])
    k.dma('sp', mask5[:], mask5_d, w=['mask5'])
    rowm = k.sb("rowm_s", [128, 2])
    k.dma('sp', rowm[:], rowm_d, w=['rowm'])
    mu1bc = k.bcast_row("mu1bc", mu1, 3 * W)
    vb = [k.bcast_row(f"vb{i}", vecs[i], W) for i in range(7)]
    w0bc, a0bc, kkbc, kabc, rkbc, lngbc, lnbbc = vb
    VK = [f"vb{i}" for i in range(7)]
    muls = k.sb("muls", [128, 3])
    k.dma('sp', muls[:], mul, w=['muls'])
    w2s = k.sb("w2s", [64, W])
    a2s = k.sb("a2s", [64, W])
    k.dma('sp', w2s[:], w2, w=['w2s'])
    k.dma('sp', a2s[:], a2, w=['a2s'])
    g2s = k.sb("g2s", [128, W])
    k.dma('sp', g2s[:], g2, w=['g2s'])
    ST = [k.sb(f"ST{i}", [64, 64], FRC) for i in range(NH)]
    zt = k.sb("zt", [128, W])
    k.memset('dve', zt[:], 0.0, ['zt'])
    for i in range(NH):
        k.cp('dve', ST[i][:], zt[0:64, 0:64], ['zt'], [f'ST{i}'])
    P1s = k.sb("P1s", [128, W], FRC)
    Us = k.sb("Us", [128, W], FRC)
    k.cp('dve', P1s[:], zt[:], ['zt'], ['P1s'])
    k.cp('dve', Us[:], zt[:], ['zt'], ['Us'])

    pt = [k.sb(f"pt{i}", [128, 3 * W]) for i in range(2)]
    pp = [k.sb(f"pp{i}", [128, 3 * W]) for i in range(2)]
    lt = [k.sb(f"lt{i}", [128, 3, 128]) for i in range(2)]
    lp = [k.sb(f"lp{i}", [128, 3, 128]) for i in range(2)]
    for i_ in range(2):
        k.memset('pool', lt[i_][:], 0.0, [f'lt{i_}0', f'lt{i_}1', f'lt{i_}2'])
        k.memset('pool', lp[i_][:], 0.0, [f'lp{i_}0', f'lp{i_}1', f'lp{i_}2', f'lp{i_}z'])
    pm = k.sb("pm", [128, 3 * W])
    vr = k.sb("vr", [128, W], FR)
    lm = k.sb("lm", [128, 3, 128])
    sw = k.sb("sw", [128, W])
    av = k.sb("av", [128, W])
    gv = k.sb("gv", [128, W])
    kkr = k.sb("kkr", [128, W])
    sq = k.sb("sq", [128, W])
    s4 = k.sb("s4", [128, NH])
    rn = k.sb("rn", [128, NH])
    nkk = k.sb("nkk", [128, W])
    kmod = k.sb("kmod", [128, W])
    kka = k.sb("kka", [128, W])
    tmp = k.sb("tmp", [128, W])
    bon = k.sb("bon", [128, NH])
    E1 = k.sb("E1", [128, W])
    E2 = k.sb("E2", [128, W])
    E3 = k.sb("E3", [128, W])
    E4 = k.sb("E4", [128, W])
    E1T = k.sb("E1T", [64, NH, 128])
    At = k.sb("At", [128, W])
    Bs = k.sb("Bs", [128, W])
    Ks = k.sb("Ks", [128, W])
    Rt = k.sb("Rt", [128, W])
    Bfm = [k.sb(f"Bfm{c}", [128, W]) for c in range(2)]
    Kfm = [k.sb(f"Kfm{c}", [128, W]) for c in range(2)]
    FT = [k.sb(f"FT{h}", [64, 4, 128], FR) for h in range(NH)]
    A5 = [k.sb(f"A5_{h}", [128, 640], FR) for h in range(NH)]
    NL = [k.sb(f"NL_{h}", [128, 256], FR) for h in range(NH)]
    PQ = [k.sb(f"PQ_{h}", [128, 256], FR) for h in range(NH)]
    W1 = k.sb("W1", [128, W], FR)
    U1 = k.sb("U1", [128, W])
    ysb = k.sb("ysb", [128, W])
    yc = k.sb("yc", [128, W])
    m4 = k.sb("m4", [128, NH])
    r4 = k.sb("r4", [128, NH])
    ot = [k.sb(f"ot{i}", [128, W]) for i in range(2)]
    B = [k.ps(f"psB{i}", [128, 512]) for i in range(8)]
    bk = lambda i: f'psB{i}'
    v3 = lambda t: t.rearrange("p (h j) -> p h j", h=NH)
    bc4 = lambda t: t.unsqueeze(2).broadcast_to([128, NH, 64])

    for i in range(NT):
        b = i % 2
        rows = slice(i * 128, (i + 1) * 128)
        PK, PPK, LTK, LPK = [], [], [], []
        for q in range(3):
            cq = slice(q * W, (q + 1) * W)
            k.dma('sp', pt[b][:, cq], prkv[q][rows, :], w=[f'pt{b}{q}'])
            PK.append(f'pt{b}{q}')
            if i == 0:
                k.dma('sp', pp[b][1:128, cq], prkv[q][0:127, :], w=[f'pp{b}{q}'])
            else:
                k.dma('sp', pp[b][:, cq], prkv[q][i * 128 - 1:i * 128 + 127, :], w=[f'pp{b}{q}'])
            PPK.append(f'pp{b}{q}')
            nr = pls[q].shape[0]
            k.dma('sp', lt[b][0:nr, q, :], pls[q][:, rows], w=[f'lt{b}{q}'])
            LTK.append(f'lt{b}{q}')
            if i == 0:
                k.dma('sp', lp[b][0:nr, q, 1:128], pls[q][:, 0:127], w=[f'lp{b}{q}'])
            else:
                k.dma('sp', lp[b][0:nr, q, :], pls[q][:, i * 128 - 1:i * 128 + 127], w=[f'lp{b}{q}'])
            LPK.append(f'lp{b}{q}')
        if i == 0:
            k.memset('pool', pp[b][0:1, :], 0.0, [f'pp{b}z'])
            k.memset('pool', lp[b][:, :, 0:1], 0.0, [f'lp{b}z'])
            PPK.append(f'pp{b}z')
            LPK.append(f'lp{b}z')
        k.tt('pool', pm[:], pp[b][:], pt[b][:], ALU.subtract, PPK + PK, ['pm'])
        k.tt('pool', pm[:], pm[:], mu1bc[:], ALU.mult, ['pm', 'mu1bc'], ['pm'])
        k.tt('pool', pm[:], pm[:], pt[b][:], ALU.add, ['pm'] + PK, ['pm'])
        r_, k_, v_ = pm[:, 0:W], pm[:, W:2 * W], pm[:, 2 * W:3 * W]
        k.cp('act', vr[:], v_, ['pm'], ['vr'])
        LK = LTK + LPK
        k.tt('dve', lm[:], lp[b][:], lt[b][:], ALU.subtract, LK, ['lm'])
        for blk in range(3):
            k.stt(lm[:, blk, :], lm[:, blk, :], muls[:, blk:blk + 1], lt[b][:, blk, :], ALU.mult, ALU.add,
                  ['lm', 'muls'] + LK, ['lm'])
        k.act(lm[0:64, 0, :], lm[0:64, 0, :], AF.Tanh, ['lm'], ['lm'])
        k.act(lm[:, 2, :], lm[:, 2, :], AF.Sigmoid, ['lm'], ['lm'])
        k.mm(B[0][:, 0:W], lm[0:64, 0, :], w2s[:], True, True, ['lm', 'w2s'], [bk(0)])
        k.mm(B[1][:, 0:W], lm[0:64, 1, :], a2s[:], True, True, ['lm', 'a2s'], [bk(1)])
        k.mm(B[2][:, 0:W], lm[:, 2, :], g2s[:], True, True, ['lm', 'g2s'], [bk(2)])
        k.tt('dve', sw[:], B[0][:, 0:W], w0bc[:], ALU.add, [bk(0), VK[0]], ['sw'])
        k.act(sw[:], sw[:], AF.Sigmoid, ['sw'], ['sw'])
        k.tt('dve', av[:], B[1][:, 0:W], a0bc[:], ALU.add, [bk(1), VK[1]], ['av'])
        k.act(av[:], av[:], AF.Sigmoid, ['av'], ['av'])
        k.cp('act', gv[:], B[2][:, 0:W], [bk(2)], ['gv'])
        k.tt('pool', kkr[:], k_, kkbc[:], ALU.mult, ['pm', VK[2]], ['kkr'])
        k.tt('pool', sq[:], kkr[:], kkr[:], ALU.mult, ['kkr'], ['sq'])
        k.P.op('dve', lambda e: e.tensor_reduce(out=s4[:], in_=v3(sq[:]), axis=AX.X, op=ALU.add), reads=['sq'], writes=['s4'])
        k.act(s4[:], s4[:], AF.Sqrt, ['s4'], ['s4'])
        k.ts('dve', s4[:], s4[:], 1e-12, None, ALU.max, None, ['s4'], ['s4'])
        k.recip(rn[:], s4[:], ['s4'], ['rn'])
        k.ts('dve', rn[:], rn[:], -1.0, None, ALU.mult, None, ['rn'], ['rn'])
        k.tt('dve', v3(nkk[:]), v3(kkr[:]), bc4(rn[:]), ALU.mult, ['kkr', 'rn'], ['nkk'])
        k.stt(tmp[:], av[:], -1.0, kabc[:], ALU.add, ALU.mult, ['av', VK[3]], ['tmp'])
        k.stt(kmod[:], tmp[:], 1.0, k_, ALU.add, ALU.mult, ['tmp', 'pm'], ['kmod'])
        k.stt(kka[:], nkk[:], -1.0, av[:], ALU.mult, ALU.mult, ['nkk', 'av'], ['kka'])
        k.tt('pool', tmp[:], r_, kmod[:], ALU.mult, ['pm', 'kmod', 'tmp'], ['tmp'])
        k.tt('pool', tmp[:], tmp[:], rkbc[:], ALU.mult, ['tmp', VK[4]], ['tmp'])
        k.P.op('dve', lambda e: e.tensor_reduce(out=bon[:], in_=v3(tmp[:]), axis=AX.X, op=ALU.add), reads=['tmp'], writes=['bon'])
        k.mm(B[3][:, 0:W], triw[:, 0, :], sw[:], True, True, ['triw', 'sw'], [bk(3)])
        k.mm(B[4][:, 0:W], triw[:, 1, :], sw[:], True, True, ['triw', 'sw'], [bk(4)])
        k.mm(B[5][:, 0:W], triw[:, 2, :], sw[:], True, True, ['triw', 'sw'], [bk(5)])
        for h in range(NH):
            k.mm(B[6 + h // 4][0:64, (h % 4) * 128:(h % 4 + 1) * 128], sw[:, h * 64:(h + 1) * 64], triw[:, 0, :], True, True,
                 ['sw', 'triw'], [bk(6 + h // 4)])
        k.act(E1[:], B[3][:, 0:W], AF.Exp, [bk(3)], ['E1'])
        k.act(E2[:], B[3][:, 0:W], AF.Exp, [bk(3)], ['E2'], scale=-1.0)
        k.act(E3[:], B[4][:, 0:W], AF.Exp, [bk(4)], ['E3'])
        k.act(E4[:], B[5][:, 0:W], AF.Exp, [bk(5)], ['E4'])
        for g in range(NG):
            k.act(E1T[:, 4 * g:4 * g + 4, :].rearrange("p a t -> p (a t)"), B[6 + g][0:64, :], AF.Exp, [bk(6 + g)], ['E1T'])
        k.tt('dve', At[:], nkk[:], E3[:], ALU.mult, ['nkk', 'E3'], ['At'])
        k.tt('pool', Bs[:], kka[:], E2[:], ALU.mult, ['kka', 'E2'], ['Bs'])
        k.tt('dve', Ks[:], kmod[:], E2[:], ALU.mult, ['kmod', 'E2'], ['Ks'])
        k.tt('pool', Rt[:], r_, E1[:], ALU.mult, ['pm', 'E1'], ['Rt'])
        for c in range(NCK):
            k.stt(Bfm[c][:], kka[:], rowm[:, c:c + 1], E4[:], ALU.mult, ALU.mult, ['kka', 'E4', 'rowm'], [f'Bfm{c}'])
            k.stt(Kfm[c][:], kmod[:], rowm[:, c:c + 1], E4[:], ALU.mult, ALU.mult, ['kmod', 'E4', 'rowm'], [f'Kfm{c}'])
        HS = list(range(NH))
        for h in HS:
            cs_ = slice(h * 64, (h + 1) * 64)
            for q, (src, key) in enumerate([(At, 'At'), (Bs, 'Bs'), (Ks, 'Ks'), (Rt, 'Rt')]):
                k.tr(B[h][0:64, q * 128:(q + 1) * 128], src[:, cs_], k.identf[:], [key], [bk(h)])
        for h in HS:
            k.cp('act' if h % 2 else 'dve', FT[h][:].rearrange("p a t -> p (a t)"), B[h][0:64, :], [bk(h)], [f'FT{h}'])
        for h in HS:
            AtT, BsT, KsT, RtT = (FT[h][:, q, :] for q in range(4))
            o = lambda j: B[h][:, j * 128:(j + 1) * 128]
            k.mm(o(0), BsT, AtT, True, True, [f'FT{h}'], [bk(h)])
            k.mm(o(1), AtT, BsT, True, True, [f'FT{h}'], [bk(h)])
            k.mm(o(2), KsT, AtT, True, True, [f'FT{h}'], [bk(h)])
        for h in HS:
            k.tt('dve', A5[h][:, 0:384], B[h][:, 0:384], mask5[:, 0:384], ALU.mult, [bk(h), 'mask5'], [f'A5_{h}'])
        for h in HS:
            AtT, BsT, KsT, RtT = (FT[h][:, q, :] for q in range(4))
            k.mm(B[h][:, 0:128], BsT, RtT, True, True, [f'FT{h}'], [bk(h)])
            k.mm(B[h][:, 128:256], KsT, RtT, True, True, [f'FT{h}'], [bk(h)])
        for h in HS:
            k.tt('dve', A5[h][:, 384:640], B[h][:, 0:256], mask5[:, 384:640], ALU.mult, [bk(h), 'mask5'], [f'A5b_{h}'])
            k.cp('act', NL[h][:], rd(A5[h][:, 0:256]), [f'A5_{h}'], [f'NL_{h}'])
            k.tt('pool' if not fr else 'dve', PQ[h][:].rearrange("p (a n) -> p a n", a=2), rd(A5[h][:, 0:256]).rearrange("p (a n) -> p a n", a=2),
                 k.identf[:].unsqueeze(1).broadcast_to([128, 2, 128]), ALU.add, [f'A5_{h}', 'ident'], [f'PQ_{h}'])
        for lev in range(nlev):
            for h in HS:
                N_, L_ = NL[h][:, 0:128], NL[h][:, 128:256]
                k.mm(B[h][:, 0:128], L_, N_, True, True, [f'NL_{h}'], [bk(h)])
                k.mm(B[h][:, 128:256], N_, L_, True, True, [f'NL_{h}'], [bk(h)])
            for h in HS:
                k.cp('act', NL[h][:], B[h][:, 0:256], [bk(h)], [f'NL_{h}'])
            for h in HS:
                N_, L_ = NL[h][:, 0:128], NL[h][:, 128:256]
                P_, Q_ = PQ[h][:, 0:128], PQ[h][:, 128:256]
                k.mm(B[h][:, 256:384], Q_, N_, True, True, [f'NL_{h}', f'PQ_{h}'], [bk(h)])
                k.mm(B[h][:, 384:512], P_, L_, True, True, [f'NL_{h}', f'PQ_{h}'], [bk(h)])
            for h in HS:
                k.tt('dve', PQ[h][:], B[h][:, 256:512], rd(PQ[h][:]), ALU.add, [bk(h), f'PQ_{h}'], [f'PQ_{h}'])
        for h in range(NH):
            k.mm(B[0][:, h * 64:(h + 1) * 64], A5[h][:, 256:384], vr[:, h * 64:(h + 1) * 64], True, True, [f'A5_{h}', 'vr'], [bk(0)])
        k.cp('act', W1[:], B[0][:, 0:W], [bk(0)], ['W1'])
        for h in range(NH):
            k.mm(B[1][:, h * 64:(h + 1) * 64], PQ[h][:, 0:128], W1[:, h * 64:(h + 1) * 64], True, True,
                 [f'PQ_{h}', 'W1'], [bk(1)])
        k.cp('act', U1[:], B[1][:, 0:W], [bk(1)], ['U1'])
        vsrc = vr if frc else None
        for c in range(NCK):
            cr = slice(c * CH, (c + 1) * CH)
            for h in range(NH):
                k.mm(B[2][cr, h * 64:(h + 1) * 64], lhc(FT[h][:, 0, cr]), ST[h][:], True, True, [f'FT{h}', f'ST{h}'], [bk(2)])
            k.cp('act', P1s[cr, :], B[2][cr, 0:W], [bk(2)], ['P1s'])
            for h in range(NH):
                k.mm(B[3][cr, h * 64:(h + 1) * 64], lhc(PQ[h][:, cr]), P1s[:, h * 64:(h + 1) * 64], True, True,
                     [f'PQ_{h}', 'P1s'], [bk(3)])
            k.tt('dve', Us[cr, :], B[3][cr, 0:W], U1[cr, :], ALU.add, [bk(3), 'U1'], ['Us'])
            for h in range(NH):
                hc_ = slice(h * 64, (h + 1) * 64)
                vh = vr[:, hc_] if frc else pm[:, 2 * W + h * 64:2 * W + (h + 1) * 64]
                vk = 'vr' if frc else 'pm'
                k.mm(B[6][cr, hc_], lhc(FT[h][:, 3, cr]), ST[h][:], True, False, [f'FT{h}', f'ST{h}'], [bk(6)])
                k.mm(B[6][cr, hc_], lhc(A5[h][:, 384:512][:, cr]), Us[:, hc_], False, False, [f'A5b_{h}', 'Us'], [bk(6)])
                k.mm(B[6][cr, hc_], lhc(A5[h][:, 512:640][:, cr]), vh, False, True, [f'A5b_{h}', vk], [bk(6)])
            for h in range(NH):
                hc_ = slice(h * 64, (h + 1) * 64)
                vh = pm[:, 2 * W + h * 64:2 * W + (h + 1) * 64]
                k.mm(B[7][0:64, hc_], Bfm[c][:, hc_], rdc(Us[:, hc_]), True, False, [f'Bfm{c}', 'Us'], [bk(7)])
                k.mm(B[7][0:64, hc_], Kfm[c][:, hc_], vh, False, True, [f'Kfm{c}', 'pm'], [bk(7)])
            for h in range(NH):
                hc_ = slice(h * 64, (h + 1) * 64)
                k.stt(ST[h][:], rdc(ST[h][:]), E1T[:, h, (c + 1) * CH - 1:(c + 1) * CH], B[7][0:64, hc_], ALU.mult, ALU.add,
                      [f'ST{h}', 'E1T', bk(7)], [f'ST{h}'])
        k.cp('act', ysb[:], B[6][:, 0:W], [bk(6)], ['ysb'])
        k.P.op('dve', lambda e: e.tensor_reduce(out=m4[:], in_=v3(ysb[:]), axis=AX.X, op=ALU.add), reads=['ysb'], writes=['m4'])
        k.ts('dve', m4[:], m4[:], -1.0 / 64.0, None, ALU.mult, None, ['m4'], ['m4'])
        k.tt('dve', v3(yc[:]), v3(ysb[:]), bc4(m4[:]), ALU.add, ['ysb', 'm4'], ['yc'])
        k.tt('pool', sq[:], yc[:], yc[:], ALU.mult, ['yc'], ['sq'])
        k.P.op('dve', lambda e: e.tensor_reduce(out=r4[:], in_=v3(sq[:]), axis=AX.X, op=ALU.add), reads=['sq'], writes=['r4'])
        k.ts('dve', r4[:], r4[:], 1.0 / 64.0, GN_EPS, ALU.mult, ALU.add, ['r4'], ['r4'])
        k.act(r4[:], r4[:], AF.Sqrt, ['r4'], ['r4'])
        k.recip(r4[:], r4[:], ['r4'], ['r4'])
        k.tt('dve', v3(yc[:]), v3(yc[:]), bc4(r4[:]), ALU.mult, ['yc', 'r4'], ['yc'])
        k.tt('pool', yc[:], yc[:], lngbc[:], ALU.mult, ['yc', VK[5]], ['yc'])
        k.tt('pool', yc[:], yc[:], lnbbc[:], ALU.add, ['yc', VK[6]], ['yc'])
        k.tt('dve', v3(tmp[:]), v3(v_), bc4(bon[:]), ALU.mult, ['pm', 'bon', 'tmp'], ['tmp'])
        k.tt('pool', yc[:], yc[:], tmp[:], ALU.add, ['yc', 'tmp'], ['yc'])
        k.tt('dve', ot[b][:], yc[:], gv[:], ALU.mult, ['yc', 'gv'], [f'ot{b}'])
        k.dma('pool', oc[rows, :], ot[b][:], r=[f'ot{b}'], final=True)
    return k.finish()


def build_RWKVP(L, k=None, CH=64):
    NH, fr = 8, True
    k = k or K()
    NT = L // 128
    W = NH * 64
    NG = NH // 4
    FR = mybir.dt.float32r if fr else F32
    rd = (lambda ap: ap.bitcast(F32)) if fr else (lambda ap: ap)
    NCK = 128 // CH
    nlev = 5 if CH == 64 else 6
    frc = True
    FRC = mybir.dt.float32r if frc else F32
    rdc = (lambda ap: ap.bitcast(F32)) if frc else (lambda ap: ap)
    lhc = (lambda ap: ap) if frc else rd
    prkv = [k.din(nm, [L, W]) for nm in ("pr", "pk", "pv")]
    mu1 = k.din("mu1", [3 * W])
    pls = [k.din("plw", [64, L]), k.din("pla", [64, L]), k.din("plg", [128, L])]
    mul = k.din("mul", [128, 3])
    w2 = k.din("w2", [64, W])
    a2 = k.din("a2", [64, W])
    g2 = k.din("g2", [128, W])
    vecs = k.din("vecs", [7, W])
    ident_d = k.din("ident", [128, 128])
    triw_d = k.din("triw", [3, 128, 128])
    mask5_d = k.din("mask5", [128, 640])
    rowm_d = k.din("rowm", [128, 2])
    oc = k.dout("oc", [L, W])

    k.consts(ident_d)
    triw = k.sb("triw_s", [128, 3, 128])
    k.dma('sp', triw[:], triw_d.rearrange("a p n -> p a n"), w=['triw'])
    mask5 = k.sb("mask5_s", [128, 640])
    k.dma('sp', mask5[:], mask5_d, w=['mask5'])
    rowm = k.sb("rowm_s", [128, 2])
    k.dma('sp', rowm[:], rowm_d, w=['rowm'])
    mu1bc = k.bcast_row("mu1bc", mu1, 3 * W)
    vb = [k.bcast_row(f"vb{i}", vecs[i], W) for i in range(7)]
    w0bc, a0bc, kkbc, kabc, rkbc, lngbc, lnbbc = vb
    VK = [f"vb{i}" for i in range(7)]
    muls = k.sb("muls", [128, 3])
    k.dma('sp', muls[:], mul, w=['muls'])
    w2s = k.sb("w2s", [64, W])
    a2s = k.sb("a2s", [64, W])
    k.dma('sp', w2s[:], w2, w=['w2s'])
    k.dma('sp', a2s[:], a2, w=['a2s'])
    g2s = k.sb("g2s", [128, W])
    k.dma('sp', g2s[:], g2, w=['g2s'])
    ST = [k.sb(f"ST{i}", [64, 64], FRC) for i in range(NH)]
    zt = k.sb("zt", [128, W])
    k.memset('dve', zt[:], 0.0, ['zt'])
    for i in range(NH):
        k.cp('dve', ST[i][:], zt[0:64, 0:64], ['zt'], [f'ST{i}'])
    P1s = k.sb("P1s", [128, W], FRC)
    Us = k.sb("Us", [128, W], FRC)
    k.cp('dve', P1s[:], zt[:], ['zt'], ['P1s'])
    k.cp('dve', Us[:], zt[:], ['zt'], ['Us'])

    pt = [k.sb("pt0", [128, 3 * W])] * 2
    pp = [k.sb("pp0", [128, 3 * W])] * 2
    lt = [k.sb("lt0", [128, 3, 128])] * 2
    lp = [k.sb("lp0", [128, 3, 128])] * 2
    k.memset('pool', lt[0][:], 0.0, ['lt0', 'lt1', 'lt2'])
    k.memset('pool', lp[0][:], 0.0, ['lp0', 'lp1', 'lp2', 'lpz'])
    pm2 = [k.sb(f"pm{i_}", [128, 3 * W]) for i_ in range(2)]
    vr2 = [k.sb(f"vr{i_}", [128, W], FR) for i_ in range(2)]
    lm2 = [k.sb(f"lm{i_}", [128, 3, 128]) for i_ in range(2)]
    sw = k.sb("sw", [128, W])
    av = k.sb("av", [128, W])
    gv2 = [k.sb(f"gv{i_}", [128, W]) for i_ in range(2)]
    kkr = k.sb("kkr", [128, W])
    sq = k.sb("sq", [128, W])
    s4 = k.sb("s4", [128, NH])
    rn = k.sb("rn", [128, NH])
    nkk = k.sb("nkk", [128, W])
    kmod = k.sb("kmod", [128, W])
    kka = k.sb("kka", [128, W])
    tmp = k.sb("tmp", [128, W])
    bon2 = [k.sb(f"bon{i_}", [128, NH]) for i_ in range(2)]
    E1 = k.sb("E1", [128, W])
    E2 = k.sb("E2", [128, W])
    E3 = k.sb("E3", [128, W])
    E4 = k.sb("E4", [128, W])
    E1T2 = [k.sb(f"E1T{i_}", [64, NH, 128]) for i_ in range(2)]
    At2 = [k.sb(f"At{i_}", [128, W]) for i_ in range(2)]
    Bs2 = [k.sb(f"Bs{i_}", [128, W]) for i_ in range(2)]
    Ks2 = [k.sb(f"Ks{i_}", [128, W]) for i_ in range(2)]
    Rt2 = [k.sb(f"Rt{i_}", [128, W]) for i_ in range(2)]
    Bfm2 = [[k.sb(f"Bfm{p_}{c}", [128, W]) for c in range(NCK)] for p_ in range(2)]
    Kfm2 = [[k.sb(f"Kfm{p_}{c}", [128, W]) for c in range(NCK)] for p_ in range(2)]
    sqp = k.sb("sqp", [128, W])
    tmpp = k.sb("tmpp", [128, W])
    FT = [k.sb(f"FT{h}", [64, 4, 128], FR) for h in range(NH)]
    A5 = [k.sb(f"A5_{h}", [128, 640], FR) for h in range(NH)]
    NL = [k.sb(f"NL_{h}", [128, 256], FR) for h in range(NH)]
    PQ = [k.sb(f"PQ_{h}", [128, 128], FR) for h in range(NH)]
    W1 = k.sb("W1", [128, W], FR)
    U1 = k.sb("U1", [128, W])
    ysb = k.sb("ysb", [128, W])
    yc = k.sb("yc", [128, W])
    m4 = k.sb("m4", [128, NH])
    r4 = k.sb("r4", [128, NH])
    ot = [k.sb(f"ot{i}", [128, W]) for i in range(2)]
    B = [k.ps(f"psB{i}", [128, 512]) for i in range(8)]
    bk = lambda i: f'psB{i}'
    v3 = lambda t: t.rearrange("p (h j) -> p h j", h=NH)
    bc4 = lambda t: t.unsqueeze(2).broadcast_to([128, NH, 64])


    S0, S1, C0, C1 = 6, 7, 4, 5

    def tile(i):
        b = i % 2
        pm, lm = pm2[b], lm2[b]
        kpm, klm = f'pm{b}', f'lm{b}'
        At, Bs, Ks, Rt, gv, vr, bon, E1T, Bf, Kf = At2[b], Bs2[b], Ks2[b], Rt2[b], gv2[b], vr2[b], bon2[b], E1T2[b], Bfm2[b], Kfm2[b]
        kAt, kBs, kKs, kRt, kgv, kvr, kbon, kE1T, kBf, kKf = (f'{n_}{b}' for n_ in ('At', 'Bs', 'Ks', 'Rt', 'gv', 'vr', 'bon', 'E1T', 'Bf', 'Kf'))
        rows = slice(i * 128, (i + 1) * 128)
        PK, PPK, LTK, LPK = [], [], [], []
        for q in range(3):
            cq = slice(q * W, (q + 1) * W)
            k.dma('sp', pt[b][:, cq], prkv[q][rows, :], w=[f'pt{q}'])
            PK.append(f'pt{q}')
            if i == 0:
                k.dma('sp', pp[b][1:128, cq], prkv[q][0:127, :], w=[f'pp{q}'])
            else:
                k.dma('sp', pp[b][:, cq], prkv[q][i * 128 - 1:i * 128 + 127, :], w=[f'pp{q}'])
            PPK.append(f'pp{q}')
            nr = pls[q].shape[0]
            k.dma('sp', lt[b][0:nr, q, :], pls[q][:, rows], w=[f'lt{q}'])
            LTK.append(f'lt{q}')
            if i == 0:
                k.dma('sp', lp[b][0:nr, q, 1:128], pls[q][:, 0:127], w=[f'lp{q}'])
            else:
                k.dma('sp', lp[b][0:nr, q, :], pls[q][:, i * 128 - 1:i * 128 + 127], w=[f'lp{q}'])
            LPK.append(f'lp{q}')
        if i == 0:
            k.memset('pool', pp[b][0:1, :], 0.0, ['ppz'])
            k.memset('pool', lp[b][:, :, 0:1], 0.0, ['lpz'])
            PPK.append('ppz')
            LPK.append('lpz')
        k.tt('dve', pm[:], pp[b][:], pt[b][:], ALU.subtract, PPK + PK, [kpm])
        k.tt('dve', pm[:], pm[:], mu1bc[:], ALU.mult, [kpm, 'mu1bc'], [kpm])
        k.tt('dve', pm[:], pm[:], pt[b][:], ALU.add, [kpm] + PK, [kpm])
        r_, k_, v_ = pm[:, 0:W], pm[:, W:2 * W], pm[:, 2 * W:3 * W]
        LK = LTK + LPK
        k.tt('dve', lm[:], lp[b][:], lt[b][:], ALU.subtract, LK, [klm])
        for blk in range(3):
            k.stt(lm[:, blk, :], lm[:, blk, :], muls[:, blk:blk + 1], lt[b][:, blk, :], ALU.mult, ALU.add,
                  [klm, 'muls'] + LK, [klm])
        k.act(lm[0:64, 0, :], lm[0:64, 0, :], AF.Tanh, [klm], [klm])
        k.act(lm[:, 2, :], lm[:, 2, :], AF.Sigmoid, [klm], [klm])
        yield
        k.cp('act', vr[:], v_, [kpm], [kvr])
        k.mm(B[S0][:, 0:W], lm[0:64, 0, :], w2s[:], True, True, [klm, 'w2s'], [bk(S0)])
        k.mm(B[S1][:, 0:W], lm[0:64, 1, :], a2s[:], True, True, [klm, 'a2s'], [bk(S1)])
        k.tt('dve', sw[:], B[S0][:, 0:W], w0bc[:], ALU.add, [bk(S0), VK[0]], ['sw'])
        k.act(sw[:], sw[:], AF.Sigmoid, ['sw'], ['sw'])
        k.tt('dve', av[:], B[S1][:, 0:W], a0bc[:], ALU.add, [bk(S1), VK[1]], ['av'])
        k.act(av[:], av[:], AF.Sigmoid, ['av'], ['av'])
        k.mm(B[S0][:, 0:W], lm[:, 2, :], g2s[:], True, True, [klm, 'g2s'], [bk(S0)])
        k.cp('act', gv[:], B[S0][:, 0:W], [bk(S0)], [kgv])
        yield
        k.tt('dve', kkr[:], k_, kkbc[:], ALU.mult, [kpm, VK[2]], ['kkr'])
        k.tt('dve', sq[:], kkr[:], kkr[:], ALU.mult, ['kkr'], ['sq'])
        k.P.op('dve', lambda e: e.tensor_reduce(out=s4[:], in_=v3(sq[:]), axis=AX.X, op=ALU.add), reads=['sq'], writes=['s4'])
        k.act(s4[:], s4[:], AF.Sqrt, ['s4'], ['s4'])
        k.ts('dve', s4[:], s4[:], 1e-12, None, ALU.max, None, ['s4'], ['s4'])
        k.recip(rn[:], s4[:], ['s4'], ['rn'])
        k.ts('dve', rn[:], rn[:], -1.0, None, ALU.mult, None, ['rn'], ['rn'])
        k.tt('dve', v3(nkk[:]), v3(kkr[:]), bc4(rn[:]), ALU.mult, ['kkr', 'rn'], ['nkk'])
        k.stt(tmp[:], av[:], -1.0, kabc[:], ALU.add, ALU.mult, ['av', VK[3]], ['tmp'])
        k.stt(kmod[:], tmp[:], 1.0, k_, ALU.add, ALU.mult, ['tmp', kpm], ['kmod'])
        k.stt(kka[:], nkk[:], -1.0, av[:], ALU.mult, ALU.mult, ['nkk', 'av'], ['kka'])
        k.tt('dve', tmp[:], r_, kmod[:], ALU.mult, [kpm, 'kmod', 'tmp'], ['tmp'])
        k.tt('dve', tmp[:], tmp[:], rkbc[:], ALU.mult, ['tmp', VK[4]], ['tmp'])
        k.P.op('dve', lambda e: e.tensor_reduce(out=bon[:], in_=v3(tmp[:]), axis=AX.X, op=ALU.add), reads=['tmp'], writes=[kbon])
        k.mm(B[S1][:, 0:W], triw[:, 0, :], sw[:], True, True, ['triw', 'sw'], [bk(S1)])
        k.mm(B[S0][:, 0:W], triw[:, 1, :], sw[:], True, True, ['triw', 'sw'], [bk(S0)])
        k.act(E1[:], B[S1][:, 0:W], AF.Exp, [bk(S1)], ['E1'])
        k.act(E2[:], B[S1][:, 0:W], AF.Exp, [bk(S1)], ['E2'], scale=-1.0)
        k.act(E3[:], B[S0][:, 0:W], AF.Exp, [bk(S0)], ['E3'])
        k.mm(B[S1][:, 0:W], triw[:, 2, :], sw[:], True, True, ['triw', 'sw'], [bk(S1)])
        k.act(E4[:], B[S1][:, 0:W], AF.Exp, [bk(S1)], ['E4'])
        for g in range(2):
            for hl in range(4):
                h = 4 * g + hl
                k.mm(B[S0 + g][0:64, hl * 128:(hl + 1) * 128], sw[:, h * 64:(h + 1) * 64], triw[:, 0, :], True, True,
                     ['sw', 'triw'], [bk(S0 + g)])
        for g in range(2):
            k.act(E1T[:, 4 * g:4 * g + 4, :].rearrange("p a t -> p (a t)"), B[S0 + g][0:64, :], AF.Exp, [bk(S0 + g)], [kE1T])
        yield
        k.tt('dve', At[:], nkk[:], E3[:], ALU.mult, ['nkk', 'E3'], [kAt])
        k.tt('dve', Bs[:], kka[:], E2[:], ALU.mult, ['kka', 'E2'], [kBs])
        k.tt('dve', Ks[:], kmod[:], E2[:], ALU.mult, ['kmod', 'E2'], [kKs])
        k.tt('dve', Rt[:], r_, E1[:], ALU.mult, [kpm, 'E1'], [kRt])
        for c in range(NCK):
            k.stt(Bf[c][:], kka[:], rowm[:, c:c + 1], E4[:], ALU.mult, ALU.mult, ['kka', 'E4', 'rowm'], [kBf])
            k.stt(Kf[c][:], kmod[:], rowm[:, c:c + 1], E4[:], ALU.mult, ALU.mult, ['kmod', 'E4', 'rowm'], [kKf])
        yield
        for g in range(2):
            HS = list(range(4 * g, 4 * g + 4))
            for h in HS:
                hl = h % 4
                cs_ = slice(h * 64, (h + 1) * 64)
                for q, (src, key) in enumerate([(At, kAt), (Bs, kBs), (Ks, kKs), (Rt, kRt)]):
                    k.tr(B[hl][0:64, q * 128:(q + 1) * 128], src[:, cs_], k.identf[:], [key], [bk(hl)])
            for h in HS:
                hl = h % 4
                k.cp('act' if h % 2 else 'dve', FT[h][:].rearrange("p a t -> p (a t)"), B[hl][0:64, :], [bk(hl)], [f'FT{h}'])
            for h in HS:
                hl = h % 4
                AtT, BsT, KsT, RtT = (FT[h][:, q, :] for q in range(4))
                k.mm(B[hl][:, 0:128], BsT, AtT, True, True, [f'FT{h}'], [bk(hl)])
                k.mm(B[hl][:, 128:256], AtT, BsT, True, True, [f'FT{h}'], [bk(hl)])
                k.mm(B[hl][:, 256:384], KsT, AtT, True, True, [f'FT{h}'], [bk(hl)])
            for h in HS:
                hl = h % 4
                k.tt('dve', A5[h][:, 0:384], B[hl][:, 0:384], mask5[:, 0:384], ALU.mult, [bk(hl), 'mask5'], [f'A5_{h}'])
            for h in HS:
                hl = h % 4
                AtT, BsT, KsT, RtT = (FT[h][:, q, :] for q in range(4))
                k.mm(B[hl][:, 0:128], BsT, RtT, True, True, [f'FT{h}'], [bk(hl)])
                k.mm(B[hl][:, 128:256], KsT, RtT, True, True, [f'FT{h}'], [bk(hl)])
            for h in HS:
                hl = h % 4
                k.tt('dve', A5[h][:, 384:640], B[hl][:, 0:256], mask5[:, 384:640], ALU.mult, [bk(hl), 'mask5'], [f'A5b_{h}'])
                k.cp('act', NL[h][:], rd(A5[h][:, 0:256]), [f'A5_{h}'], [f'NL_{h}'])
                k.tt('dve', PQ[h][:, 0:128], rd(A5[h][:, 0:128]), k.identf[:], ALU.add, [f'A5_{h}', 'ident'], [f'PQ_{h}'])
            for lev in range(nlev):
                last = (lev == nlev - 1)
                for h in HS:
                    hl = h % 4
                    N_, L_ = NL[h][:, 0:128], NL[h][:, 128:256]
                    k.mm(B[hl][:, 0:128], L_, N_, True, True, [f'NL_{h}'], [bk(hl)])
                    k.mm(B[hl][:, 128:256], N_, L_, True, True, [f'NL_{h}'], [bk(hl)])
                for h in HS:
                    hl = h % 4
                    k.cp('act', NL[h][:], B[hl][:, 0:256], [bk(hl)], [f'NL_{h}'])
                for h in HS:
                    hl = h % 4
                    k.mm(B[hl][:, 256:384], NL[h][:, 128:256], PQ[h][:, 0:128], True, True, [f'NL_{h}', f'PQ_{h}'], [bk(hl)])
                for h in HS:
                    hl = h % 4
                    k.tt('dve', PQ[h][:, 0:128], B[hl][:, 256:384], rd(PQ[h][:, 0:128]), ALU.add, [bk(hl), f'PQ_{h}'], [f'PQ_{h}'])
            yield
        for h in range(NH):
            k.mm(B[C0][:, h * 64:(h + 1) * 64], A5[h][:, 256:384], vr[:, h * 64:(h + 1) * 64], True, True, [f'A5_{h}', kvr], [bk(C0)])
        k.cp('act', W1[:], B[C0][:, 0:W], [bk(C0)], ['W1'])
        for h in range(NH):
            k.mm(B[C1][:, h * 64:(h + 1) * 64], PQ[h][:, 0:128], W1[:, h * 64:(h + 1) * 64], True, True,
                 [f'PQ_{h}', 'W1'], [bk(C1)])
        k.cp('act', U1[:], B[C1][:, 0:W], [bk(C1)], ['U1'])
        for c in range(NCK):
            cr = slice(c * CH, (c + 1) * CH)
            for h in range(NH):
                k.mm(B[C0][:, h * 64:(h + 1) * 64], FT[h][:, 0, :], ST[h][:], True, True, [f'FT{h}', f'ST{h}'], [bk(C0)])
            k.cp('act', P1s[cr, :], B[C0][cr, 0:W], [bk(C0)], ['P1s'])
            for h in range(NH):
                k.mm(B[C0][:, h * 64:(h + 1) * 64], PQ[h][:, :], P1s[:, h * 64:(h + 1) * 64], True, True,
                     [f'PQ_{h}', 'P1s'], [bk(C0)])
            k.tt('dve', Us[cr, :], B[C0][cr, 0:W], U1[cr, :], ALU.add, [bk(C0), 'U1'], ['Us'])
            for h in range(NH):
                hc_ = slice(h * 64, (h + 1) * 64)
                k.mm(B[C0][:, hc_], FT[h][:, 3, :], ST[h][:], True, False, [f'FT{h}', f'ST{h}'], [bk(C0)])
                k.mm(B[C0][:, hc_], A5[h][:, 384:512], Us[:, hc_], False, False, [f'A5b_{h}', 'Us'], [bk(C0)])
                k.mm(B[C0][:, hc_], A5[h][:, 512:640], vr[:, hc_], False, True, [f'A5b_{h}', kvr], [bk(C0)])
            k.cp('act', ysb[cr, :], B[C0][cr, 0:W], [bk(C0)], ['ysb'])
            for h in range(NH):
                hc_ = slice(h * 64, (h + 1) * 64)
                k.mm(B[C1][0:64, hc_], Bf[c][:, hc_], rdc(Us[:, hc_]), True, False, [kBf, 'Us'], [bk(C1)])
                k.mm(B[C1][0:64, hc_], Kf[c][:, hc_], rd(vr[:, hc_]), False, True, [kKf, kvr], [bk(C1)])
            for h in range(NH):
                hc_ = slice(h * 64, (h + 1) * 64)
                k.stt(ST[h][:], rdc(ST[h][:]), E1T[:, h, (c + 1) * CH - 1:(c + 1) * CH], B[C1][0:64, hc_], ALU.mult, ALU.add,
                      [f'ST{h}', kE1T, bk(C1)], [f'ST{h}'])
        k.P.op('dve', lambda e: e.tensor_reduce(out=m4[:], in_=v3(ysb[:]), axis=AX.X, op=ALU.add), reads=['ysb'], writes=['m4'])
        k.ts('dve', m4[:], m4[:], -1.0 / 64.0, None, ALU.mult, None, ['m4'], ['m4'])
        k.tt('dve', v3(yc[:]), v3(ysb[:]), bc4(m4[:]), ALU.add, ['ysb', 'm4'], ['yc'])
        k.tt('dve', sqp[:], yc[:], yc[:], ALU.mult, ['yc'], ['sqp'])
        k.P.op('dve', lambda e: e.tensor_reduce(out=r4[:], in_=v3(sqp[:]), axis=AX.X, op=ALU.add), reads=['sqp'], writes=['r4'])
        k.ts('dve', r4[:], r4[:], 1.0 / 64.0, GN_EPS, ALU.mult, ALU.add, ['r4'], ['r4'])
        k.act(r4[:], r4[:], AF.Sqrt, ['r4'], ['r4'])
        k.recip(r4[:], r4[:], ['r4'], ['r4'])
        k.tt('dve', v3(yc[:]), v3(yc[:]), bc4(r4[:]), ALU.mult, ['yc', 'r4'], ['yc'])
        k.tt('dve', yc[:], yc[:], lngbc[:], ALU.mult, ['yc', VK[5]], ['yc'])
        k.tt('dve', yc[:], yc[:], lnbbc[:], ALU.add, ['yc', VK[6]], ['yc'])
        k.tt('dve', v3(tmpp[:]), v3(rd(vr[:])), bc4(bon[:]), ALU.mult, [kvr, kbon], ['tmpp'])
        k.tt('dve', yc[:], yc[:], tmpp[:], ALU.add, ['yc', 'tmpp'], ['yc'])
        k.tt('dve', ot[b][:], yc[:], gv[:], ALU.mult, ['yc', kgv], [f'ot{b}'])
        k.dma('pool', oc[rows, :], ot[b][:], r=[f'ot{b}'], final=True)

    gens = {}

    def adv(j):
        if 0 <= j < NT:
            try:
                next(gens[j])
            except StopIteration:
                pass

    for step in range(NT + 2):
        if step < NT:
            gens[step] = tile(step)
            adv(step)
        for r_i in range(3):
            adv(step - 1)
            adv(step - 2)
    return k.finish()


def rwkv_consts(CH=64):
    c = -math.exp(-0.5)
    blk = np.kron(np.eye(128 // CH), np.ones((CH, CH)))
    s_idx = np.arange(128)[:, None]
    t_idx = np.arange(128)[None, :]
    triw = np.stack([c * blk * (s_idx <= t_idx), c * blk * (s_idx < t_idx), c * blk * (s_idx > t_idx)]).astype(np.float32)
    lt_, le_, gt_ = blk * (s_idx < t_idx), blk * (s_idx <= t_idx), blk * (t_idx < s_idx)
    mask5 = np.concatenate([lt_, gt_, lt_, le_, le_], 1).astype(np.float32)
    rowm = np.stack([(np.arange(128) < 64), (np.arange(128) >= 64)], 1).astype(np.float32) if CH == 64 else np.ones((128, 2), np.float32)
    return dict(ident=np.eye(128, dtype=np.float32), triw=triw, mask5=mask5, rowm=rowm)


def rwkv_host_inputs(s, p_rwkv, prm, NH=4, CH=64):
    L = p_rwkv.shape[0]
    cs = slice(64 * NH * s, 64 * NH * (s + 1))
    r_, w1, k_, v_, a1, g1 = np.split(p_rwkv, np.cumsum([512, 64, 512, 512, 64])[:5], axis=-1)
    mu = prm['rwkv_mu']
    mur, muw1, muk, muv, mua1, mug1 = np.split(mu, np.cumsum([512, 64, 512, 512, 64])[:5])
    zm = np.zeros(64, np.float32)
    mul = np.concatenate([muw1, zm, mua1, zm, mug1]).reshape(3, 128).T
    vecs = np.stack([prm['rwkv_w0'][cs], prm['rwkv_a0'][cs], prm['rwkv_k_k'][cs], prm['rwkv_k_a'][cs],
                     prm['rwkv_r_k'].reshape(-1)[cs], prm['rwkv_ln_gain'][cs], prm['rwkv_ln_bias'][cs]])
    c_ = np.ascontiguousarray
    d = dict(pr=c_(r_[:, cs]), pk=c_(k_[:, cs]), pv=c_(v_[:, cs]),
             mu1=c_(np.concatenate([mur[cs], muk[cs], muv[cs]])),
             plw=c_(w1.T), pla=c_(a1.T), plg=c_(g1.T), mul=c_(mul),
             w2=c_(prm['rwkv_w2'][:, cs]), a2=c_(prm['rwkv_a2'][:, cs]),
             g2=c_(prm['rwkv_g2'][:, cs]), vecs=c_(vecs))
    d.update(rwkv_consts(CH))
    return d


FM0 = [(0, 128, 0), (128, 128, 128), (256, 128, 256), (384, 128, 384), (1536, 16, 512)] + \
      [(1552 + j * 128, 128, 528 + j * 128) for j in range(4)]
NF0 = 1040
FM1 = [(512, 64, 0), (1600, 64, 64), (1664, 128, 128)] + [(1792 + j * 128, 128, 256 + j * 128) for j in range(8)]
NF1 = 1280


def host_params(inp):
    c_ = lambda a: np.ascontiguousarray(np.asarray(a), dtype=np.float32)
    P = {}
    P['ident'] = np.eye(128, dtype=np.float32)
    P['triu'] = np.triu(np.ones((128, 128), np.float32))
    P['trigt'] = np.tril(np.ones((128, 128), np.float32), -1)
    for l in range(2):
        for j in range(7):
            P[f'g{l}_{j}'] = c_(inp['norm_gain'][l][j])
        for nm in ('xa_wq', 'xa_wk', 'xa_wv', 'xa_wo', 'mlp_w1', 'mlp_w2'):
            P[f'{nm}{l}'] = c_(inp[nm][l])
    P['w_in0'] = c_(inp['ab_w_in'][0])
    P['w_in1'] = c_(inp['cd_w_in'][0])
    P['w_out0'] = c_(inp['ab_w_out'][0])
    P['w_out1'] = c_(inp['cd_w_out'][0])
    P['wglu'] = c_(inp['s5_w_glu'][0])
    P['bglu'] = c_(inp['s5_b_glu'][0])
    prm0 = {k_: np.asarray(inp[k_][0]) for k_ in inp if k_.startswith('s5_') or k_.startswith('gla_')}
    prm1 = {k_: np.asarray(inp[k_][0]) for k_ in inp if k_.startswith('rwkv_') or k_.startswith('lru_')}
    for s in range(2):
        cs = slice(s * 128, (s + 1) * 128)
        P[f'gla_w2_{s}'] = c_(prm0['gla_w_decay2'][:, cs])
        P[f'gla_bd_{s}'] = c_(prm0['gla_b_decay'][None, cs])
        P[f'gla_gn_{s}'] = c_(prm0['gla_norm_gain'][2 * s:2 * s + 2].reshape(256))
        d = s5_host_inputs(s, np.zeros((2, 512), np.float32), prm0)
        for nm in ('lam_re', 'lam_im', 'lstep', 'Bre', 'Bim', 'Cre', 'Cim', 'dsk'):
            P[f's5_{nm}_{s}'] = c_(d[nm])
        P['iota_p'] = c_(d['iota_p'])
        P['iota_f'] = c_(d['iota_f'])
        if s == 0:
            d = rwkv_host_inputs(0, np.zeros((2, 1792), np.float32), prm1, 8, 64)
            for nm in ('mu1', 'mul', 'w2', 'a2', 'g2', 'vecs'):
                P[f'rw_{nm}'] = c_(d[nm])
            for nm in ('triw', 'mask5', 'rowm'):
                P[f'rw_{nm}'] = c_(d[nm])
        d = lru_host_inputs(s, np.zeros((2, 512), np.float32), np.zeros((2, 512), np.float32), prm1)
        for nm in ('cw', 'cb', 'Wa', 'Wx', 'ba', 'bx', 'lam'):
            P[f'lru_{nm}_{s}'] = c_(d[nm])
    return P


def build_fused(P, L):
    k = K(fused=True)
    X = {nm: k.xin(nm, a.shape) for nm, a in P.items()}
    x = k.xin('x', [L, D])
    mem = k.xin('mem', [256, D])
    out = k.xout('out', [L, D])
    proj0 = k.scratch('proj0', [L, 2064])
    PT0 = k.scratch('PT0', [NF0, L])
    proj1 = k.scratch('proj1', [L, 2816])
    PT1 = k.scratch('PT1', [NF1, L])
    o = k.scratch('o', [L, D])
    odT = k.scratch('odT', [512, L])
    h1 = k.scratch('h1', [L, D])
    h2 = k.scratch('h2', [L, D])
    h3 = k.scratch('h3', [L, D])

    def cblock(l, hin, hout, glu, ob_fm):
        io = dict(oa=o[:, 0:512], hin=hin, wout=X[f'w_out{l}'], g1=X[f'g{l}_1'], ident=X['ident'], hout=h1)
        if ob_fm:
            io['obT'] = odT
        else:
            io['ob'] = o[:, 512:1024]
        if glu:
            io.update(wglu=X['wglu'], bglu=X['bglu'])
        k.begin_phase(f'C1_{l}', io)
        build_C1(L, glu, k=k, ob_fm=ob_fm)
        k.begin_phase(f'C2_{l}', dict(hin=h1, mem=mem, wq=X[f'xa_wq{l}'], wk=X[f'xa_wk{l}'], wv=X[f'xa_wv{l}'], wo=X[f'xa_wo{l}'],
                                      g2=X[f'g{l}_2'], g3=X[f'g{l}_3'], g6=X[f'g{l}_6'], ident=X['ident'], hout=h2))
        build_C2(L, k=k)
        k.begin_phase(f'C3_{l}', dict(hin=h2, w1=X[f'mlp_w1{l}'], w2=X[f'mlp_w2{l}'], g4=X[f'g{l}_4'], g5=X[f'g{l}_5'],
                                      ident=X['ident'], hout=hout))
        build_C3(L, k=k)

    k.begin_phase('A0', dict(x=x, gain=X['g0_0'], W=X['w_in0'], ident=X['ident'], out=proj0, outT=PT0))
    build_A2(L, 2064, FM0, NF0, k=k)
    for s in range(2):
        io_g = dict(qT=PT0[s * 128:(s + 1) * 128, :], kT=PT0[256 + s * 128:256 + (s + 1) * 128, :],
                    ktok=proj0[:, 256 + s * 128:256 + (s + 1) * 128], v=proj0[:, 512 + s * 256:512 + (s + 1) * 256],
                    gate=proj0[:, 1024 + s * 256:1024 + (s + 1) * 256], dlrT=PT0[512:528, :],
                    w2=X[f'gla_w2_{s}'], bdec=X[f'gla_bd_{s}'], gn=X[f'gla_gn_{s}'], triu=X['triu'],
                    trigt=X['trigt'], oa=o[:, s * 256:(s + 1) * 256])
        k.begin_phase(f'GLA{s}', io_g)
        build_GLA(L, k=k)
    for s in range(2):
        io_s = dict(uT=PT0[528 + s * 256:528 + (s + 1) * 256, :], u=proj0[:, 1552 + s * 256:1552 + (s + 1) * 256],
                    triu=X['triu'], iota_p=X['iota_p'], iota_f=X['iota_f'], y=o[:, 512 + s * 256:512 + (s + 1) * 256])
        for nm in ('lam_re', 'lam_im', 'lstep', 'Bre', 'Bim', 'Cre', 'Cim', 'dsk'):
            io_s[nm] = X[f's5_{nm}_{s}']
        k.begin_phase(f'S5{s}', io_s)
        build_S5(L, k=k)
    cblock(0, x, h3, True, False)
    k.begin_phase('A1', dict(x=h3, gain=X['g1_0'], W=X['w_in1'], ident=X['ident'], out=proj1, outT=PT1))
    build_A2(L, 2816, FM1, NF1, k=k)
    io = dict(pr=proj1[:, 0:512], pk=proj1[:, 576:1088], pv=proj1[:, 1088:1600], plw=PT1[0:64, :], pla=PT1[64:128, :],
              plg=PT1[128:256, :], ident=X['ident'], triw=X['rw_triw'], mask5=X['rw_mask5'], rowm=X['rw_rowm'], oc=o[:, 0:512])
    for nm in ('mu1', 'mul', 'w2', 'a2', 'g2', 'vecs'):
        io[nm] = X[f'rw_{nm}']
    k.begin_phase('RW', io)
    build_RWKVP(L, k=k, CH=64)
    streams = []
    for s in range(2):
        io = dict(xbT=PT1[256 + s * 256:256 + (s + 1) * 256, :], gateT=PT1[768 + s * 256:768 + (s + 1) * 256, :],
                  odT=odT[s * 256:(s + 1) * 256, :])
        for nm in ('cw', 'cb', 'Wa', 'Wx', 'ba', 'bx', 'lam'):
            io[nm] = X[f'lru_{nm}_{s}']
        streams.append((f'l{s}_', io, lambda kk: gen_LRU(L, kk)))
    k.begin_phase('LRU', {})
    run_streams(k, streams)
    k.finish()
    cblock(1, h3, out, False, True)
    return k.finish_program()


BATCH, SEQ = 4, 4096
_CACHE = {}


def kernel(**inp):
    inp = {k_: np.asarray(v_) for k_, v_ in inp.items()}
    P = host_params(inp)
    if 'nc' not in _CACHE:
        _CACHE['nc'] = build_fused(P, SEQ)
    nc = _CACHE['nc']
    maps = []
    for b in range(BATCH):
        m = dict(P)
        m['x'] = np.ascontiguousarray(inp['x'][b], dtype=np.float32)
        m['mem'] = np.ascontiguousarray(inp['mem'][b], dtype=np.float32)
        maps.append(m)
    res = run_bass_kernel_spmd(nc, maps, core_ids=list(range(BATCH))).results
    return np.ascontiguousarray(np.stack([res[b]['out'] for b in range(BATCH)]).astype(np.float32))
```

```python
import os
import math
from contextlib import ExitStack


import numpy as np
import concourse.bass as bass
import concourse.mybir as mybir
from concourse.bass_utils import run_bass_kernel_spmd

F32 = mybir.dt.float32
BF16 = mybir.dt.bfloat16
I32 = mybir.dt.int32
AF = mybir.ActivationFunctionType
ALU = mybir.AluOpType
AX = mybir.AxisListType

ENGS = ['pe', 'act', 'dve', 'pool', 'sp']
NDMA_SLOTS = 8
SAME_ENGINE_SYNC = os.environ.get("NOSELF", "0") != "1"


class Prog:
    def __init__(self, nc):
        self.nc = nc
        self.ops = {e: [] for e in ENGS}
        self.cnt = {e: 0 for e in ENGS}
        self.last_w = {}
        self.readers = {}
        self.seen = {e: {} for e in ENGS}
        self.dma_n = {e: 0 for e in ENGS}
        self.dma_tok = {e: [None] * NDMA_SLOTS for e in ENGS}
        self.final_tokens = []
        from contextlib import ExitStack
        self.sem_stack = ExitStack()
        self.sems = {}
        for e in ['pe', 'act', 'dve', 'pool']:
            self.sems[('c', e)] = self.sem_stack.enter_context(nc.semaphore("s_c_" + e))
        for q in ['sp', 'pool']:
            for sl in range(NDMA_SLOTS):
                self.sems[('d', q, sl)] = self.sem_stack.enter_context(nc.semaphore(f"s_d_{q}_{sl}"))

    def barrier(self):
        toks = []
        for e in ['pe', 'act', 'dve', 'pool']:
            if self.cnt[e] > 0:
                toks.append((('c', e), self.cnt[e]))
        for q in ENGS:
            for t in self.dma_tok[q]:
                if t is not None:
                    toks.append(t)
        for e in ENGS:
            waits = []
            for (sem, val) in toks:
                if sem == ('c', e):
                    continue
                if self.seen[e].get(sem, 0) >= val:
                    continue
                waits.append((sem, val))
                self.seen[e][sem] = val
            if waits:
                self.ops[e].append((waits, None, None))
        self.last_w = {}
        self.readers = {}

    def _deps(self, eng, reads, writes):
        toks = []
        for r in reads:
            t = self.last_w.get(r)
            if t is not None:
                toks.append(t)
        for w in writes:
            t = self.last_w.get(w)
            if t is not None:
                toks.append(t)
            toks.extend(self.readers.get(w, []))
        need = {}
        for (sem, val) in toks:
            if not SAME_ENGINE_SYNC and sem == ('c', eng):
                continue
            if sem == ('c', 'pe') and eng == 'pe':
                continue
            if self.seen[eng].get(sem, 0) >= val:
                continue
            if need.get(sem, 0) < val:
                need[sem] = val
        for sem, val in need.items():
            self.seen[eng][sem] = val
        return list(need.items())

    def _commit(self, tok, reads, writes):
        for w in writes:
            self.last_w[w] = tok
            self.readers[w] = []
        for r in reads:
            if r in writes:
                continue
            self.readers.setdefault(r, []).append(tok)

    def op(self, eng, fn, reads=(), writes=()):
        self.nrec = getattr(self, 'nrec', 0) + 1
        if self.nrec > int(os.environ.get("MAXOPS", "100000000")):
            return None
        kp = getattr(self, 'key_prefix', '')
        reads = [r if r.startswith('ps') else kp + r for r in reads]
        writes = [w if w.startswith('ps') else kp + w for w in writes]
        pk = getattr(self, 'ps_prefix', '')
        reads = [('ps' + pk + r[2:]) if r.startswith('ps') else r for r in reads]
        writes = [('ps' + pk + w[2:]) if w.startswith('ps') else w for w in writes]
        writes = list(writes) + [r for r in reads if r.startswith('ps') and r not in writes]
        waits = self._deps(eng, reads, writes)
        self.cnt[eng] += 1
        tok = (('c', eng), self.cnt[eng])
        self.ops[eng].append((waits, fn, tok))
        self._commit(tok, reads, writes)
        return tok

    def dma(self, q, out, in_, reads=(), writes=(), final=False, **kw):
        self.nrec = getattr(self, 'nrec', 0) + 1
        if self.nrec > int(os.environ.get("MAXOPS", "100000000")):
            return None
        kp = getattr(self, 'key_prefix', '')
        reads = [kp + r for r in reads]
        writes = [kp + w for w in writes]
        waits = self._deps(q, reads, writes)
        n = self.dma_n[q]
        slot = n % NDMA_SLOTS
        prev = self.dma_tok[q][slot]
        if prev is not None and self.seen[q].get(prev[0], 0) < prev[1]:
            waits.append(prev)
            self.seen[q][prev[0]] = prev[1]
        tok = (('d', q, slot), 16 * (n // NDMA_SLOTS + 1))
        self.dma_n[q] += 1
        self.dma_tok[q][slot] = tok

        def fn(e, out=out, in_=in_, kw=kw):
            return e.dma_start(out=out, in_=in_, **kw)
        self.ops[q].append((waits, fn, tok))
        self._commit(tok, reads, writes)
        if final:
            self.final_tokens.append(tok)
        return tok

    def emit(self, last=True):
        nc = self.nc
        sems = self.sems
        with nc.Block() as block:
            final = list(self.final_tokens) if last else []

            def run(e, name):
                for waits, fn, tok in self.ops[name]:
                    for (s, v) in waits:
                        e.wait_ge(sems[s], v)
                    if fn is None:
                        continue
                    inst = fn(e)
                    inc = 16 if tok[0][0] == 'd' else 1
                    inst.then_inc(sems[tok[0]], inc)
                if name == 'sp':
                    for (s, v) in final:
                        e.wait_ge(sems[s], v)
                self.ops[name] = []

            @block.tensor
            def _(e):
                run(e, 'pe')

            @block.scalar
            def _(e):
                run(e, 'act')

            @block.vector
            def _(e):
                run(e, 'dve')

            @block.gpsimd
            def _(e):
                run(e, 'pool')

            @block.sync
            def _(e):
                run(e, 'sp')
        if last:
            self.sem_stack.close()


D = 1024
KC = 8
EPS = 1e-6


class K:
    def __init__(self, fused=False):
        self.nc = bass.Bass("TRN2", target_bir_lowering=False)
        self.st = ExitStack()
        self.P = Prog(self.nc)
        self.n = 0
        self.fused = fused
        self.io = {}
        self.pfx = ""

    def begin_phase(self, name, io):
        self.pfx = name + "_"
        self.io = io
        self.st = ExitStack()
        for a in ('wstage', 'rr_cache', 'identf', 'identb'):
            if hasattr(self, a):
                delattr(self, a)

    def scratch(self, name, shape, dt=F32):
        return self.nc.dram_tensor(name, list(shape), dt, kind="Internal").ap()

    def xin(self, name, arr_shape, dt=F32):
        return self.nc.dram_tensor(name, list(arr_shape), dt, kind="ExternalInput").ap()

    def xout(self, name, arr_shape, dt=F32):
        return self.nc.dram_tensor(name, list(arr_shape), dt, kind="ExternalOutput").ap()

    def din(self, name, shape, dt=F32):
        if self.fused:
            ap = self.io[name]
            assert list(ap.shape) == list(shape), (name, ap.shape, shape)
            return ap
        return self.nc.dram_tensor(name, list(shape), dt, kind="ExternalInput").ap()

    def dout(self, name, shape, dt=F32):
        if self.fused:
            ap = self.io[name]
            assert list(ap.shape) == list(shape), (name, ap.shape, shape)
            return ap
        return self.nc.dram_tensor(name, list(shape), dt, kind="ExternalOutput").ap()

    def sb(self, name, shape, dt=F32):
        pers = getattr(self, 'persist', None)
        if pers is not None and (self.pfx + name) in pers:
            return pers[self.pfx + name]
        return self.st.enter_context(self.nc.sbuf_tensor(self.pfx + name, list(shape), dt))

    def push_scope(self, persistent):
        self.persist = getattr(self, 'persist', None) or {}
        for (name, shape, dt) in persistent:
            self.persist[self.pfx + name] = self.st.enter_context(self.nc.sbuf_tensor(self.pfx + name, list(shape), dt))
        self._st_saved = self.st
        self.st = ExitStack()

    def pop_scope(self):
        self.P.barrier()
        self.P.emit(last=False)
        self.st.close()
        self.st = self._st_saved

    def ps(self, name, shape, dt=F32):
        return self.st.enter_context(self.nc.psum_tensor(self.pfx + name, list(shape), dt))

    def finish(self, last=True):
        if self.fused:
            self.P.barrier()
            self.P.emit(last=False)
            self.st.close()
            return None
        self.P.emit()
        self.st.close()
        return self.nc

    def finish_program(self):
        self.P.emit(last=True)
        return self.nc

    def mm(self, out, lhsT, rhs, start, stop, r, w):
        self.P.op('pe', lambda e: e.matmul(out, lhsT=lhsT, rhs=rhs, start=start, stop=stop), reads=r, writes=w)

    def tr(self, out, in_, ident, r, w):
        self.P.op('pe', lambda e: e.transpose(out=out, in_=in_, identity=ident), reads=list(r) + ['ident'], writes=w)

    def act(self, out, in_, func, r, w, **kw):
        self.P.op('act', lambda e: e.activation(out=out, in_=in_, func=func, **kw), reads=r, writes=w)

    def tt(self, eng, out, in0, in1, op, r, w):
        self.P.op(eng, lambda e: e.tensor_tensor(out=out, in0=in0, in1=in1, op=op), reads=r, writes=w)

    def ts(self, eng, out, in0, s1, s2, op0, op1, r, w):
        if op1 is None:
            self.P.op(eng, lambda e: e.tensor_scalar(out=out, in0=in0, scalar1=s1, scalar2=None, op0=op0), reads=r, writes=w)
        else:
            self.P.op(eng, lambda e: e.tensor_scalar(out=out, in0=in0, scalar1=s1, scalar2=s2, op0=op0, op1=op1), reads=r, writes=w)

    def stt(self, out, in0, scalar, in1, op0, op1, r, w):
        self.P.op('dve', lambda e: e.scalar_tensor_tensor(out=out, in0=in0, scalar=scalar, in1=in1, op0=op0, op1=op1),
                  reads=r, writes=w)

    def cp(self, eng, out, in_, r, w):
        if eng == 'act':
            self.P.op('act', lambda e: e.copy(out=out, in_=in_), reads=r, writes=w)
        else:
            self.P.op(eng, lambda e: e.tensor_copy(out=out, in_=in_), reads=r, writes=w)

    def recip(self, out, in_, r, w):
        self.P.op('dve', lambda e: e.reciprocal(out=out, in_=in_), reads=r, writes=w)

    def memset(self, eng, ap, val, w):
        self.P.op(eng, lambda e: e.memset(ap, val), reads=[], writes=w)

    def dma(self, q, out, in_, r=(), w=(), final=False, **kw):
        self.P.dma(q, out, in_, reads=r, writes=w, final=final, **kw)

    def consts(self, ident_d):
        self.identf = self.sb("identf", [128, 128], F32)
        self.identb = self.sb("identb", [128, 128], BF16)
        self.dma('sp', self.identf[:], ident_d, w=['ident'])
        self.cp('dve', self.identb[:], self.identf[:], ['ident'], ['ident'])

    def gain_cols(self, name, g_d):
        t = self.sb(name, [128, KC], F32)
        self.dma('sp', t[:], g_d.rearrange("(kc p) -> p kc", p=128), w=[name], allow_slow_non_contiguous=True)
        return t

    def bcast_row(self, name, vec_d, n):
        t = self.sb(name, [128, n], F32)
        self.dma('sp', t[:], vec_d.partition_broadcast(128), w=[name])
        return t

    def load_weight(self, name, w_d, kchunks, ncols, gcol=None, gkey=None, stage_cols=2048, q='sp'):
        wb = self.sb(name, [128, kchunks, ncols], BF16)
        if not hasattr(self, 'wstage'):
            self.wstage = [self.sb(f"wstage{i}", [128, stage_cols], F32) for i in range(2)]
            self.wstage_n = 0
            self.wstage_cols = stage_cols
        sc = self.wstage_cols
        wv = w_d.rearrange("(kc p) n -> p kc n", p=128)
        for kc in range(kchunks):
            for c0 in range(0, ncols, sc):
                cw = min(sc, ncols - c0)
                b = self.wstage_n % 2
                self.wstage_n += 1
                stg = self.wstage[b]
                self.dma(q, stg[:, 0:cw], wv[:, kc, c0:c0 + cw], w=[f'wstage{b}'])
                eng = 'act' if (kc % 2 == 0) else 'dve'
                if gcol is not None:
                    if eng == 'act':
                        self.act(wb[:, kc, c0:c0 + cw], stg[:, 0:cw], AF.Copy, [f'wstage{b}', gkey], [f'{name}{kc}'],
                                 scale=gcol[:, kc:kc + 1])
                    else:
                        self.ts('dve', wb[:, kc, c0:c0 + cw], stg[:, 0:cw], gcol[:, kc:kc + 1], None, ALU.mult, None,
                                [f'wstage{b}', gkey], [f'{name}{kc}'])
                else:
                    self.cp(eng, wb[:, kc, c0:c0 + cw], stg[:, 0:cw], [f'wstage{b}'], [f'{name}{kc}'])
        return wb

    def rstd_of(self, x_ap, xkey, ss, rstd, junk, key, ncols=D):
        self.act(junk, x_ap, AF.Square, [xkey], ['junk', key + 'ss'], accum_out=ss)
        self.ts('dve', rstd, ss, 1.0 / ncols, EPS, ALU.mult, ALU.add, [key + 'ss'], [key])
        self.act(rstd, rstd, AF.Sqrt, [key], [key])
        self.recip(rstd, rstd, [key], [key])


def pipeline(make_gen, n):
    active = []
    for i in range(n):
        for g in list(active):
            try:
                next(g)
            except StopIteration:
                active.remove(g)
        g = make_gen(i)
        active.append(g)
        try:
            next(g)
        except StopIteration:
            active.remove(g)
    while active:
        for g in list(active):
            try:
                next(g)
            except StopIteration:
                active.remove(g)


def pipeline_gen(make_gen, n):
    active = []
    for i in range(n):
        for g in list(active):
            try:
                next(g)
            except StopIteration:
                active.remove(g)
        g = make_gen(i)
        active.append(g)
        try:
            next(g)
        except StopIteration:
            active.remove(g)
        yield
    while active:
        for g in list(active):
            try:
                next(g)
            except StopIteration:
                active.remove(g)
        yield


def run_streams(k, streams):
    base_pfx = k.pfx
    gens = []
    for (pf, io, gf) in streams:
        gens.append([pf, io, None, gf])
    active = list(gens)
    while active:
        for st in list(active):
            pf, io, g, gf = st
            k.pfx = base_pfx + pf
            k.P.key_prefix = pf
            k.P.ps_prefix = pf
            k.io = io
            try:
                if g is None:
                    st[2] = gf(k)
                    g = st[2]
                next(g)
            except StopIteration:
                active.remove(st)
    k.pfx = base_pfx
    k.P.key_prefix = ''
    k.P.ps_prefix = ''


GELU_C = 1.5957691216057308


def norm_T(k, xt, xkey, xn, xnkey, xT_dst, xTkey, psT, psTkey, ss, rstd, junk, key, evac_eng='act'):
    k.rstd_of(xt, xkey, ss, rstd, junk, key)
    k.ts('dve', xn, xt, rstd, None, ALU.mult, None, [xkey, key], [xnkey])
    for kc in range(KC):
        k.tr(psT[:, kc * 128:(kc + 1) * 128], xn[:, kc * 128:(kc + 1) * 128], k.identb[:], [xnkey], [psTkey])
    k.cp(evac_eng, xT_dst, psT[:].rearrange("p (k t) -> p k t", k=KC), [psTkey], [xTkey])


def post_norm_res(k, ps2, pskeys, ht, hkey, gbc, gkey, tmp2, tmpkeys, ss2, rstd, junk, key):
    for j in range(2):
        k.act(junk[:, 0:512], ps2[j], AF.Square, [pskeys[j]], ['junk', key + f'ss{j}'], accum_out=ss2[:, j:j + 1])
    k.tt('dve', ss2[:, 0:1], ss2[:, 0:1], ss2[:, 1:2], ALU.add, [key + 'ss0', key + 'ss1'], [key + 'ss0'])
    k.ts('dve', rstd, ss2[:, 0:1], 1.0 / D, EPS, ALU.mult, ALU.add, [key + 'ss0'], [key])
    k.act(rstd, rstd, AF.Sqrt, [key], [key])
    k.recip(rstd, rstd, [key], [key])
    for j in range(2):
        sl = slice(j * 512, (j + 1) * 512)
        k.stt(tmp2[j], ps2[j], rstd, gbc[:, sl], ALU.mult, ALU.mult, [pskeys[j], key, gkey], [tmpkeys[j]])
        k.tt('pool', ht[:, sl], ht[:, sl], tmp2[j], ALU.add, [tmpkeys[j], hkey], [hkey])


def build_C1(NTOK, glu, k=None, ob_fm=False):
    k = k or K()
    NT = NTOK // 128
    oa = k.din("oa", [NTOK, 512])
    if ob_fm:
        obT = k.din("obT", [512, NTOK])
    else:
        ob = k.din("ob", [NTOK, 512])
    hin = k.din("hin", [NTOK, D])
    wout = k.din("wout", [D, D])
    g1 = k.din("g1", [D])
    ident_d = k.din("ident", [128, 128])
    if glu:
        wglu = k.din("wglu", [512, 512])
        bglu = k.din("bglu", [512])
    hout = k.dout("hout", [NTOK, D])
    k.consts(ident_d)
    g1bc = k.bcast_row("g1bc", g1, D)
    Wout = k.load_weight("Wout", wout, KC, D, stage_cols=1024)
    if glu:
        Wglu = k.load_weight("Wglu", wglu, 4, 512)
        bgbc = k.bcast_row("bgbc", bglu, 512)

    def ring(nm, shape, n, dt=F32):
        return [k.sb(f"{nm}{j}", shape, dt) for j in range(n)]
    oc = ring("oc", [128, D], 10 if glu else 4)
    ocb = ring("ocb", [128, D], 3, BF16)
    oT = ring("oT", [128, KC, 128], 3, BF16)
    ht = ring("ht", [128, D], 4)
    mix = ring("mix", [128, D], 5)
    tmp = ring("tmp", [128, D], 3)
    ss2 = ring("ss2", [128, 2], 4)
    rstd = ring("rstd", [128, 1], 5)
    junk = k.sb("junk", [128, D], BF16)
    if ob_fm:
        obt = ring("obt", [128, 4, 128], 4)
    if glu:
        yb = ring("yb", [128, 512], 3, BF16)
        yT = ring("yT", [128, 4, 128], 3, BF16)
        t1 = ring("t1", [128, 512], 9)
        zs = ring("zs", [128, 512], 4)
        psTg = k.ps("psTg", [128, D], BF16)
        psG = k.ps("psG", [128, 512])
    psTm = [k.ps(f"psTm{j}", [128, D], BF16) for j in range(2)]
    psM = [k.ps(f"psM{j}", [128, 512]) for j in range(4)]

    def tile(i):
        rows = slice(i * 128, (i + 1) * 128)
        def T(lst, nm):
            j = i % len(lst)
            return lst[j], f'{nm}{j}'
        oc_, koc = T(oc, 'oc'); ocb_, kocb = T(ocb, 'ocb'); oT_, koT = T(oT, 'oT'); ht_, kht = T(ht, 'ht')
        mix_, kmix = T(mix, 'mix'); tmp_, ktmp = T(tmp, 'tmp'); ss_, kss = T(ss2, 'ss2'); rs_, krs = T(rstd, 'rstd')
        pm = [psM[2 * (i % 2)], psM[2 * (i % 2) + 1]]
        kpm = [f'psM{2 * (i % 2)}', f'psM{2 * (i % 2) + 1}']
        ptm, kptm = psTm[i % 2], f'psTm{i % 2}'
        kA, kB = koc + 'A', koc + 'B'
        k.dma('sp', oc_[:, 0:512], oa[rows, :], w=[kA])
        if ob_fm:
            obt_, kobt = T(obt, 'obt')
            k.dma('sp', obt_[:], obT[:, rows].rearrange("(a p) t -> p a t", p=128), w=[kobt])
        else:
            k.dma('sp', oc_[:, 512:1024], ob[rows, :], w=[kB])
        yield
        if glu:
            y = oc_[:, 512:1024]
            yb_, kyb = T(yb, 'yb'); yT_, kyT = T(yT, 'yT'); t1_, kt1 = T(t1, 't1'); zs_, kzs = T(zs, 'zs')
            k.cp('dve', yb_[:], y, [kB], [kyb])
            k.act(t1_[:], y, AF.Square, [kB], [kt1])
            k.act(t1_[:], t1_[:], AF.Copy, [kt1], [kt1], scale=0.044715, bias=1.0)
            yield
            for kc in range(4):
                k.tr(psTg[:, kc * 128:(kc + 1) * 128], yb_[:, kc * 128:(kc + 1) * 128], k.identb[:], [kyb], ['psTg'])
            k.tt('pool', t1_[:], t1_[:], y, ALU.mult, [kt1, kB], [kt1])
            yield
            k.cp('act', yT_[:], psTg[:, 0:512].rearrange("p (k t) -> p k t", k=4), ['psTg'], [kyT])
            k.act(t1_[:], t1_[:], AF.Sigmoid, [kt1], [kt1], scale=GELU_C)
            yield
            for kc in range(4):
                k.mm(psG[:], yT_[:, kc, :], Wglu[:, kc, :], kc == 0, kc == 3, [kyT, f'Wglu{kc}'], ['psG'])
            yield
            k.tt('dve', zs_[:], psG[:], bgbc[:], ALU.add, ['psG', 'bgbc'], [kzs])
            yield
            k.act(zs_[:], zs_[:], AF.Sigmoid, [kzs], [kzs])
            yield
            k.tt('dve', zs_[:], t1_[:], zs_[:], ALU.mult, [kt1, kzs], [kzs])
            k.tt('dve', y, y, zs_[:], ALU.mult, [kB, kzs], [kB])
        if ob_fm:
            k.cp('dve', ocb_[:, 0:512], oc_[:, 0:512], [kA], [kocb])
            k.cp('pool', oT_[:, 4:8, :], obt_[:], [kobt], [koT + 'b'])
        else:
            k.cp('dve', ocb_[:], oc_[:], [kA, kB], [kocb])
        yield
        nk = 4 if ob_fm else KC
        for kc in range(nk):
            k.tr(ptm[:, kc * 128:(kc + 1) * 128], ocb_[:, kc * 128:(kc + 1) * 128], k.identb[:], [kocb], [kptm])
        yield
        k.cp('act', oT_[:, 0:nk, :], ptm[:, 0:nk * 128].rearrange("p (k t) -> p k t", k=nk), [kptm], [koT])
        yield
        for cg in range(2):
            for kc in range(KC):
                ok_ = (koT + 'b') if (ob_fm and kc >= 4) else koT
                k.mm(pm[cg][:], oT_[:, kc, :], Wout[:, kc, cg * 512:(cg + 1) * 512], kc == 0, kc == KC - 1,
                     [ok_, f'Wout{kc}'], [kpm[cg]])
        yield
        for j in range(2):
            k.act(junk[:, 0:512], pm[j][:], AF.Square, [kpm[j]], ['junk', kss], accum_out=ss_[:, j:j + 1])
        for j in range(2):
            k.cp('act', mix_[:, j * 512:(j + 1) * 512], pm[j][:], [kpm[j]], [kmix])
        k.dma('sp', ht_[:], hin[rows, :], w=[kht])
        yield
        k.tt('dve', ss_[:, 0:1], ss_[:, 0:1], ss_[:, 1:2], ALU.add, [kss], [kss])
        k.ts('dve', rs_[:], ss_[:, 0:1], 1.0 / D, EPS, ALU.mult, ALU.add, [kss], [krs])
        yield
        k.act(rs_[:], rs_[:], AF.Sqrt, [krs], [krs])
        yield
        k.recip(rs_[:], rs_[:], [krs], [krs])
        k.stt(tmp_[:], mix_[:], rs_[:], g1bc[:], ALU.mult, ALU.mult, [kmix, krs, 'g1bc'], [ktmp])
        yield
        k.tt('pool', ht_[:], ht_[:], tmp_[:], ALU.add, [kht, ktmp], [kht])
        k.dma('pool', hout[rows, :], ht_[:], r=[kht], final=True)

    pipeline(tile, NT)
    return k.finish()


def build_C3(NTOK, k=None):
    k = k or K()
    NB = NTOK // 512
    DFF = 4096
    FC = DFF // 128
    hin = k.din("hin", [NTOK, D])
    w1 = k.din("w1", [D, DFF])
    w2 = k.din("w2", [DFF, D])
    g4 = k.din("g4", [D])
    g5 = k.din("g5", [D])
    ident_d = k.din("ident", [128, 128])
    hout = k.dout("hout", [NTOK, D])
    k.consts(ident_d)
    g4c = k.gain_cols("g4c", g4)
    g5bc = k.bcast_row("g5bc", g5, D)
    W1 = k.load_weight("W1", w1, KC, DFF, gcol=g4c, gkey='g4c', stage_cols=512)
    W2 = k.load_weight("W2", w2, FC, D, stage_cols=512)
    ht = [k.sb(f"ht{i}", [128, D]) for i in range(4)]
    xn = [k.sb(f"xn{i}", [128, D], BF16) for i in range(2)]
    xT = k.sb("xT", [128, KC, 512], BF16)
    AT = k.sb("AT", [128, FC, 512], BF16)
    sq = [k.sb(f"sq{i}", [128, 512]) for i in range(2)]
    junk = k.sb("junk", [128, D], BF16)
    ss = [k.sb(f"ss{i}", [128, 1]) for i in range(2)]
    ss2 = [k.sb(f"ss2{i}", [128, 2]) for i in range(2)]
    rstd = [k.sb(f"rstd{i}", [128, 1]) for i in range(2)]
    rstd2 = [k.sb(f"rstdb{i}", [128, 1]) for i in range(2)]
    psT = k.ps("psT", [128, D], BF16)
    psU = [k.ps(f"psU{i}", [128, 512]) for i in range(3)]
    psD = [k.ps(f"psD{i}", [128, 512]) for i in range(4)]
    nu = 0
    for blk in range(NB):
        for tt in range(4):
            i = blk * 4 + tt
            b = i % 2
            rows = slice(i * 128, (i + 1) * 128)
            k.dma('sp', ht[tt][:], hin[rows, :], w=[f'ht{tt}'])
            norm_T(k, ht[tt][:], f'ht{tt}', xn[b][:], f'xn{b}', xT[:, :, tt * 128:(tt + 1) * 128], 'xT', psT[:], 'psT',
                   ss[b][:], rstd[b][:], junk[:], f'n{b}')
        for fc in range(FC):
            pu = nu % 3
            nu += 1
            for kc in range(KC):
                k.mm(psU[pu][:], W1[:, kc, fc * 128:(fc + 1) * 128], xT[:, kc, :], kc == 0, kc == KC - 1,
                     [f'W1{kc}', 'xT'], [f'psU{pu}'])
            sb_ = fc % 2
            k.act(sq[sb_][:], psU[pu][:], AF.Square, [f'psU{pu}'], [f'sq{sb_}'])
            k.stt(AT[:, fc, :], psU[pu][:], 0.0, sq[sb_][:], ALU.is_gt, ALU.mult, [f'psU{pu}', f'sq{sb_}'], ['AT'])
        for tt in range(4):
            i = blk * 4 + tt
            b = i % 2
            rows = slice(i * 128, (i + 1) * 128)
            for cg in range(2):
                pd = 2 * b + cg
                for fc in range(FC):
                    k.mm(psD[pd][:], AT[:, fc, tt * 128:(tt + 1) * 128], W2[:, fc, cg * 512:(cg + 1) * 512],
                         fc == 0, fc == FC - 1, ['AT', f'W2{fc}'], [f'psD{pd}'])
            post_norm_res(k, [psD[2 * b][:], psD[2 * b + 1][:]], [f'psD{2 * b}', f'psD{2 * b + 1}'], ht[tt], f'ht{tt}',
                          g5bc, 'g5bc', [sq[0][:], sq[1][:]], ['sq0', 'sq1'], ss2[b], rstd2[b][:], junk, f'pn{b}')
            k.dma('pool', hout[rows, :], ht[tt][:], r=[f'ht{tt}'], final=True)
    return k.finish()


def build_C2(NTOK, k=None):
    k = k or K()
    NB = NTOK // 512
    MEM = 256
    hin = k.din("hin", [NTOK, D])
    mem = k.din("mem", [MEM, D])
    wq = k.din("wq", [D, D])
    wk = k.din("wk", [D, D])
    wv = k.din("wv", [D, D])
    wo = k.din("wo", [D, D])
    g2 = k.din("g2", [D])
    g3 = k.din("g3", [D])
    g6 = k.din("g6", [D])
    ident_d = k.din("ident", [128, 128])
    hout = k.dout("hout", [NTOK, D])
    k.consts(ident_d)
    g2c = k.gain_cols("g2c", g2)
    g6c = k.gain_cols("g6c", g6)
    g3bc = k.bcast_row("g3bc", g3, D)
    Wk = k.load_weight("Wk", wk, KC, D, gcol=g6c, gkey='g6c', stage_cols=1024)
    Wv = k.load_weight("Wv", wv, KC, D, gcol=g6c, gkey='g6c', stage_cols=1024)
    Wq = k.load_weight("Wq", wq, KC, D, gcol=g2c, gkey='g2c', stage_cols=1024)
    Wo = k.load_weight("Wo", wo, KC, D, stage_cols=1024)
    ht = [k.sb(f"ht{i}", [128, D]) for i in range(2)]
    xn = [k.sb(f"xn{i}", [128, D], BF16) for i in range(2)]
    xT = [k.sb(f"xT{i}", [128, KC, 512], BF16) for i in range(2)]
    memT = k.sb("memT", [128, KC, MEM], BF16)
    KT = k.sb("KT", [128, KC, MEM], BF16)
    V = k.sb("V", [128, 2, D], BF16)
    QT = [k.sb(f"QT{i}", [128, KC, 512], BF16) for i in range(2)]
    Pm = [k.sb(f"Pm{i}", [128, 4, MEM], BF16) for i in range(3)]
    Pn = [k.sb(f"Pn{i}", [128, 4, MEM], BF16) for i in range(3)]
    PT = [k.sb(f"PT{i}", [128, 8, 128], BF16) for i in range(3)]
    OT = [k.sb(f"OT{i}", [128, KC, 128], BF16) for i in range(3)]
    tmp = [k.sb(f"tmp{i}", [128, 512]) for i in range(2)]
    junk = k.sb("junk", [128, D], BF16)
    ss = [k.sb(f"ss{i}", [128, 1]) for i in range(2)]
    ss2 = [k.sb(f"ss2{i}", [128, 2]) for i in range(2)]
    rstd = [k.sb(f"rstd{i}", [128, 1]) for i in range(2)]
    rstd2 = [k.sb(f"rstdb{i}", [128, 1]) for i in range(2)]
    mx = [k.sb(f"mx{i}", [128, 4]) for i in range(3)]
    sm = [k.sb(f"sm{i}", [128, 4]) for i in range(3)]
    psT = k.ps("psT", [128, D], BF16)
    psA = k.ps("psA", [128, 1024])
    psS = k.ps("psS", [128, 1024])
    psX = k.ps("psX", [128, 1024])
    for mt in range(2):
        k.dma('sp', ht[mt][:], mem[mt * 128:(mt + 1) * 128, :], w=[f'ht{mt}'])
        norm_T(k, ht[mt][:], f'ht{mt}', xn[mt][:], f'xn{mt}', memT[:, :, mt * 128:(mt + 1) * 128], 'memT', psT[:], 'psT',
               ss[mt][:], rstd[mt][:], junk[:], f'n{mt}')
    for cc in range(KC):
        pa = cc % 2
        for kc in range(KC):
            k.mm(psA[:, pa * 512:pa * 512 + MEM], Wk[:, kc, cc * 128:(cc + 1) * 128], memT[:, kc, :], kc == 0, kc == KC - 1,
                 [f'Wk{kc}', 'memT'], [f'psA{pa}'])
        k.cp('act' if cc % 2 else 'dve', KT[:, cc, :], psA[:, pa * 512:pa * 512 + MEM], [f'psA{pa}'], [f'KT{cc}'])
    for mt in range(2):
        for cg in range(2):
            for kc in range(KC):
                k.mm(psX[:, cg * 512:(cg + 1) * 512], memT[:, kc, mt * 128:(mt + 1) * 128], Wv[:, kc, cg * 512:(cg + 1) * 512],
                     kc == 0, kc == KC - 1, ['memT', f'Wv{kc}'], [f'psX{cg}'])
            k.cp('act' if cg else 'dve', V[:, mt, cg * 512:(cg + 1) * 512], psX[:, cg * 512:(cg + 1) * 512], [f'psX{cg}'], [f'V{mt}{cg}'])
    xt6 = [k.sb(f"xt6_{i}", [128, D]) for i in range(6)]
    ss6 = [k.sb(f"ss6_{i}", [128, 1]) for i in range(4)]
    rs6 = [k.sb(f"rs6_{i}", [128, 1]) for i in range(5)]
    xn3 = [k.sb(f"xn3_{i}", [128, D], BF16) for i in range(3)]
    psTx = k.ps("psTx", [128, D], BF16)

    def tile(i):
        blk, tt = divmod(i, 4)
        xb = blk % 2
        b = i % 3
        rows = slice(i * 128, (i + 1) * 128)
        tsl = slice(tt * 128, (tt + 1) * 128)
        def T(lst, nm):
            j = i % len(lst)
            return lst[j], f'{nm}{j}'
        xt_, kxt = T(xt6, 'xt6'); ss_, kss = T(ss6, 'ss6'); rs_, krs = T(rs6, 'rs6'); xn_, kxn = T(xn3, 'xn3')
        hb = i % 2
        k.dma('sp', xt_[:], hin[rows, :], w=[kxt])
        yield
        k.act(junk[:], xt_[:], AF.Square, [kxt], ['junk', kss], accum_out=ss_[:])
        yield
        k.ts('dve', rs_[:], ss_[:], 1.0 / D, EPS, ALU.mult, ALU.add, [kss], [krs])
        yield
        k.act(rs_[:], rs_[:], AF.Sqrt, [krs], [krs])
        yield
        k.recip(rs_[:], rs_[:], [krs], [krs])
        k.ts('dve', xn_[:], xt_[:], rs_[:], None, ALU.mult, None, [kxt, krs], [kxn])
        yield
        for kc in range(KC):
            k.tr(psTx[:, kc * 128:(kc + 1) * 128], xn_[:, kc * 128:(kc + 1) * 128], k.identb[:], [kxn], ['psTx'])
        yield
        k.cp('act', xT[xb][:, :, tsl], psTx[:].rearrange("p (k t) -> p k t", k=KC), ['psTx'], [f'xT{xb}'])
        yield
        if tt == 3:
            for cc in range(KC):
                pa = cc % 2
                for kc in range(KC):
                    k.mm(psA[:, pa * 512:(pa + 1) * 512], Wq[:, kc, cc * 128:(cc + 1) * 128], xT[xb][:, kc, :], kc == 0, kc == KC - 1,
                         [f'Wq{kc}', f'xT{xb}'], [f'psA{pa}'])
                k.cp('act' if cc % 2 else 'dve', QT[xb][:, cc, :], psA[:, pa * 512:(pa + 1) * 512], [f'psA{pa}'], [f'QT{xb}{cc}'])
        yield
        yield
        yield
        yield
        for h in range(4):
            sb_ = h // 2
            for j in range(2):
                cc = 2 * h + j
                k.mm(psS[:, h * MEM:(h + 1) * MEM], QT[xb][:, cc, tsl], KT[:, cc, :], j == 0, j == 1,
                     [f'QT{xb}{cc}', f'KT{cc}'], [f'psS{sb_}'])
        k.P.op('dve', lambda e, b=b: e.tensor_reduce(out=mx[b][:], in_=psS[:].rearrange("p (h m) -> p h m", h=4),
                                                    axis=AX.X, op=ALU.max),
               reads=['psS0', 'psS1'], writes=[f'mx{b}'])
        k.ts('dve', mx[b][:], mx[b][:], -1.0 / 16.0, None, ALU.mult, None, [f'mx{b}'], [f'mx{b}'])
        for h in range(4):
            k.act(Pm[b][:, h, :], psS[:, h * MEM:(h + 1) * MEM], AF.Exp, [f'psS{h // 2}', f'mx{b}'], [f'Pm{b}', f'sm{b}'],
                  scale=1.0 / 16.0, bias=mx[b][:, h:h + 1], accum_out=sm[b][:, h:h + 1])
        k.recip(sm[b][:], sm[b][:], [f'sm{b}'], [f'sm{b}'])
        k.tt('dve', Pn[b][:], Pm[b][:], sm[b][:].unsqueeze(2).broadcast_to([128, 4, MEM]), ALU.mult,
             [f'Pm{b}', f'sm{b}'], [f'Pn{b}'])
        yield
        for h in range(4):
            for mt in range(2):
                k.tr(psT[:, (h * 2 + mt) * 128:(h * 2 + mt + 1) * 128], Pn[b][:, h, mt * 128:(mt + 1) * 128], k.identb[:],
                     [f'Pn{b}'], ['psT'])
        k.cp('act', PT[b][:], psT[:].rearrange("p (k t) -> p k t", k=8), ['psT'], [f'PT{b}'])
        for cc in range(KC):
            h = cc // 2
            pa = cc // 4
            for mt in range(2):
                k.mm(psA[:, cc * 128:(cc + 1) * 128], V[:, mt, cc * 128:(cc + 1) * 128], PT[b][:, h * 2 + mt, :],
                     mt == 0, mt == 1, [f'V{mt}{cc // 4}', f'PT{b}'], [f'psA{pa}'])
        k.cp('dve', OT[b][:, 0:4, :], psA[:, 0:512].rearrange("p (k t) -> p k t", k=4), ['psA0'], [f'OT{b}_0'])
        k.cp('act', OT[b][:, 4:8, :], psA[:, 512:1024].rearrange("p (k t) -> p k t", k=4), ['psA1'], [f'OT{b}_1'])
        k.dma('sp', ht[hb][:], hin[rows, :], w=[f'ht{hb}'])
        yield
        for cg in range(2):
            for cc in range(KC):
                k.mm(psX[:, cg * 512:(cg + 1) * 512], OT[b][:, cc, :], Wo[:, cc, cg * 512:(cg + 1) * 512],
                     cc == 0, cc == KC - 1, [f'OT{b}_{cc // 4}', f'Wo{cc}'], [f'psX{cg}'])
        post_norm_res(k, [psX[:, 0:512], psX[:, 512:1024]], ['psX0', 'psX1'], ht[hb], f'ht{hb}',
                      g3bc, 'g3bc', [tmp[0][:], tmp[1][:]], ['tmp0', 'tmp1'], ss2[b % 2], rstd2[b % 2][:], junk, f'pn{b % 2}')
        k.dma('pool', hout[rows, :], ht[hb][:], r=[f'ht{hb}'], final=True)

    pipeline(tile, NTOK // 128)
    return k.finish()


def build_A2(NTOK, NC, fm, NF, k=None):
    k = k or K()
    NB = NTOK // 512
    x = k.din("x", [NTOK, D])
    gain = k.din("gain", [D])
    W = k.din("W", [D, NC])
    ident_d = k.din("ident", [128, 128])
    out = k.dout("out", [NTOK, NC])
    outT = k.dout("outT", [NF, NTOK])
    k.consts(ident_d)
    gc = k.gain_cols("gc", gain)
    Wb = k.load_weight("Wb", W, KC, NC, gcol=gc, gkey='gc', stage_cols=1408)
    cgs = [(c0, min(512, NC - c0)) for c0 in range(0, NC, 512)]
    def ring(nm, shape, n, dt=F32):
        return [k.sb(f"{nm}{j}", shape, dt) for j in range(n)]
    xt = ring("xt", [128, D], 6)
    xn = ring("xn", [128, D], 3, BF16)
    xT = [k.sb(f"xT{i}", [128, KC, 512], BF16) for i in range(2)]
    ot = [k.sb(f"ot{i}", [128, NC]) for i in range(2)]
    ft = [k.sb(f"ft{i}", [128, 512]) for i in range(2)]
    junk = k.sb("junk", [128, D], BF16)
    ss = ring("ss", [128, 1], 4)
    rstd = ring("rstd", [128, 1], 5)
    psT = k.ps("psT", [128, D], BF16)
    psO = [k.ps(f"psO{i}", [128, 512]) for i in range(4)]
    psF = [k.ps(f"psF{i}", [128, 512]) for i in range(2)]
    cnt = {'no': 0, 'nf': 0}

    def tile(i):
        blk, tt = divmod(i, 4)
        xb = blk % 2
        def T(lst, nm):
            j = i % len(lst)
            return lst[j], f'{nm}{j}'
        xt_, kxt = T(xt, 'xt'); xn_, kxn = T(xn, 'xn'); ss_, kss = T(ss, 'ss'); rs_, krs = T(rstd, 'rstd')
        k.dma('sp', xt_[:], x[i * 128:(i + 1) * 128, :], w=[kxt])
        yield
        k.act(junk[:], xt_[:], AF.Square, [kxt], ['junk', kss], accum_out=ss_[:])
        yield
        k.ts('dve', rs_[:], ss_[:], 1.0 / D, EPS, ALU.mult, ALU.add, [kss], [krs])
        yield
        k.act(rs_[:], rs_[:], AF.Sqrt, [krs], [krs])
        yield
        k.recip(rs_[:], rs_[:], [krs], [krs])
        k.ts('dve', xn_[:], xt_[:], rs_[:], None, ALU.mult, None, [kxt, krs], [kxn])
        yield
        for kc in range(KC):
            k.tr(psT[:, kc * 128:(kc + 1) * 128], xn_[:, kc * 128:(kc + 1) * 128], k.identb[:], [kxn], ['psT'])
        yield
        k.cp('act', xT[xb][:, :, tt * 128:(tt + 1) * 128], psT[:].rearrange("p (k t) -> p k t", k=KC), ['psT'], [f'xT{xb}'])
        yield
        if tt != 3:
            return
        for t2 in range(4):
            i2 = blk * 4 + t2
            b = i2 % 2
            for ci, (c0, cw) in enumerate(cgs):
                pb = cnt['no'] % 4
                cnt['no'] += 1
                for kc in range(KC):
                    k.mm(psO[pb][:, 0:cw], xT[xb][:, kc, t2 * 128:(t2 + 1) * 128], Wb[:, kc, c0:c0 + cw], kc == 0, kc == KC - 1,
                         [f'xT{xb}', f'Wb{kc}'], [f'psO{pb}'])
                k.cp('dve' if pb % 2 == 0 else 'act', ot[b][:, c0:c0 + cw], psO[pb][:, 0:cw], [f'psO{pb}'], [f'ot{b}_{pb % 2}'])
            k.dma('pool', out[i2 * 128:(i2 + 1) * 128, :], ot[b][:], r=[f'ot{b}_0', f'ot{b}_1'], final=True)
        for (c0, cw, r0) in fm:
            pf = cnt['nf'] % 2
            cnt['nf'] += 1
            for kc in range(KC):
                k.mm(psF[pf][0:cw, :], Wb[:, kc, c0:c0 + cw], xT[xb][:, kc, :], kc == 0, kc == KC - 1,
                     [f'Wb{kc}', f'xT{xb}'], [f'psF{pf}'])
            k.cp('dve' if pf == 0 else 'act', ft[pf][0:cw, :], psF[pf][0:cw, :], [f'psF{pf}'], [f'ft{pf}'])
            k.dma('pool', outT[r0:r0 + cw, blk * 512:(blk + 1) * 512], ft[pf][0:cw, :], r=[f'ft{pf}'], final=True)

    pipeline(tile, NTOK // 128)
    return k.finish()


def gen_GLA(L, k):
    NT = L // 128
    qT = k.din("qT", [128, L])
    kT = k.din("kT", [128, L])
    ktok = k.din("ktok", [L, 128])
    v = k.din("v", [L, 256])
    gate = k.din("gate", [L, 256])
    dlrT = k.din("dlrT", [16, L])
    w2 = k.din("w2", [16, 128])
    bdec = k.din("bdec", [1, 128])
    gn = k.din("gn", [256])
    triu_d = k.din("triu", [128, 128])
    trigt_d = k.din("trigt", [128, 128])
    oa = k.dout("oa", [L, 256])

    triu = k.sb("triu_s", [128, 128])
    trigt = k.sb("trigt_s", [128, 128])
    k.dma('sp', triu[:], triu_d, w=['triu'])
    k.dma('sp', trigt[:], trigt_d, w=['trigt'])
    w2s = k.sb("w2s", [16, 128])
    k.dma('sp', w2s[:], w2, w=['w2s'])
    bds = k.sb("bds", [1, 128])
    k.dma('sp', bds[:], bdec, w=['bds'])
    ones1 = k.sb("ones1", [1, 128])
    k.memset('dve', ones1[:], 1.0, ['ones1'])
    gnbc = k.bcast_row("gnbc", gn, 256)
    S = k.sb("S", [128, 128], mybir.dt.float32r)
    zS = k.sb("zS", [128, 128])
    k.memset('dve', zS[:], 0.0, ['zS'])
    k.cp('dve', S[:], zS[:], ['zS'], ['S'])
    rm = k.sb("rm", [128, 2])
    k.memset('dve', rm[:], 0.0, ['rm'])
    k.memset('dve', rm[0:64, 0:1], 0.125, ['rm'])
    k.memset('dve', rm[64:128, 1:2], 0.125, ['rm'])

    def ring(nm, shape, n, dt=F32):
        return [k.sb(f"{nm}{j}", shape, dt) for j in range(n)]
    FR_ = mybir.dt.float32r
    triur = k.sb("triur", [128, 128], FR_)
    trigtr = k.sb("trigtr", [128, 128], FR_)
    k.cp('dve', triur[:], triu[:], ['triu'], ['triur'])
    k.cp('dve', trigtr[:], trigt[:], ['trigt'], ['trigtr'])
    vr = ring("vr", [128, 256], 10, FR_)
    qTt, kTt, kt, gt = ring("qTt", [128, 128], 8), ring("kTt", [128, 128], 8), ring("kt", [128, 128], 8), ring("gt", [128, 256], 8)
    vt = ring("vt", [128, 256], 11)
    dt_ = ring("dt", [16, 128], 3)
    la = ring("la", [128, 128], 4, mybir.dt.float32r)
    sg = ring("sg", [128, 256], 16)
    EqT, EkT, Eks = ring("EqT", [128, 128], 7), ring("EkT", [128, 128], 3), ring("Eks", [128, 128], 3)
    qin, kin, kst = ring("qin", [128, 2, 128], 5, mybir.dt.float32r), ring("kin", [128, 128], 3, mybir.dt.float32r), ring("kst", [128, 128], 5, mybir.dt.float32r)
    sc0, sc1 = ring("sc0_", [128, 128], 3, mybir.dt.float32r), ring("sc1_", [128, 128], 3, mybir.dt.float32r)
    osr = ring("osr", [128, 256], 6)
    osb = ring("osb", [128, 256], 3)
    ss, rs = ring("ss", [128, 2], 4), ring("rs", [128, 2], 5)
    ot = ring("ot", [128, 256], 3)
    junk = k.sb("junk", [128, 128])
    psZ = [k.ps(f"psZ{j}", [128, 512]) for j in range(2)]
    psA = [k.ps(f"psA{j}", [128, 512]) for j in range(2)]
    psB = [k.ps(f"psB{j}", [128, 512]) for j in range(2)]
    psC = [k.ps(f"psC{j}", [128, 512]) for j in range(2)]

    def tile(i):
        rows = slice(i * 128, (i + 1) * 128)
        R = lambda lst: (lst[i % len(lst)], f'{lst[0].name if hasattr(lst[0], "name") else id(lst)}_{i % len(lst)}')
        def T(lst, nm):
            j = i % len(lst)
            return lst[j], f'{nm}{j}'
        q_, kq = T(qTt, 'qTt'); kT_, kkT = T(kTt, 'kTt'); kt_, kkt = T(kt, 'kt'); v_, kv = T(vt, 'vt'); g_, kg = T(gt, 'gt')
        d_, kd = T(dt_, 'dt'); la_, kla = T(la, 'la'); sg_, ksg = T(sg, 'sg')
        Eq, kEq = T(EqT, 'EqT'); Ek, kEk = T(EkT, 'EkT'); Es, kEs = T(Eks, 'Eks')
        qi, kqi = T(qin, 'qin'); ki, kki = T(kin, 'kin'); ks, kks = T(kst, 'kst')
        scs = [T(sc0, 'sc0_'), T(sc1, 'sc1_')]
        orw, korw = T(osr, 'osr'); ob_, kob = T(osb, 'osb'); ss_, kss = T(ss, 'ss'); rs_, krs = T(rs, 'rs'); ot_, kot = T(ot, 'ot')
        pz, kpz = psZ[i % 2], f'psZ{i % 2}'
        pa, kpa = psA[i % 2], f'psA{i % 2}'
        pb, kpb = psB[i % 2], f'psB{i % 2}'
        pc, kpc = psC[i % 2], f'psC{i % 2}'
        k.dma('sp', q_[:], qT[:, rows], w=[kq])
        k.dma('sp', kT_[:], kT[:, rows], w=[kkT])
        k.dma('sp', kt_[:], ktok[rows, :], w=[kkt])
        k.dma('sp', v_[:], v[rows, :], w=[kv])
        k.dma('sp', g_[:], gate[rows, :], w=[kg])
        k.dma('sp', d_[:], dlrT[:, rows], w=[kd])
        yield
        k.mm(pz[:, 0:128], d_[:], w2s[:], True, False, [kd, 'w2s'], [kpz])
        k.mm(pz[:, 0:128], ones1[:], bds[:], False, True, ['ones1', 'bds'], [kpz])
        yield
        k.act(la_[:], pz[:, 0:128], AF.Exp, [kpz], [kla], scale=-1.0)
        k.act(la_[:], la_[:].bitcast(F32), AF.Ln, [kla], [kla], bias=1.0)
        k.act(sg_[:], g_[:], AF.Exp, [kg], [ksg], scale=-1.0)
        vr_, kvr = T(vr, 'vr')
        k.cp('act', vr_[:], v_[:], [kv], [kvr])
        yield
        k.ts('dve', la_[:], la_[:].bitcast(F32), -1.0 / 16.0, None, ALU.mult, None, [kla], [kla])
        k.ts('dve', sg_[:], sg_[:], 1.0, None, ALU.add, None, [ksg], [ksg])
        k.recip(sg_[:], sg_[:], [ksg], [ksg])
        yield
        k.mm(pa[:, 0:128], la_[:], triur[:], True, True, [kla, 'triur'], [kpa])
        k.mm(pa[:, 128:256], trigtr[:], la_[:], True, True, [kla, 'trigtr'], [kpa])
        yield
        k.act(Eq[:], pa[:, 0:128], AF.Exp, [kpa], [kEq])
        k.act(Ek[:], pa[:, 0:128], AF.Exp, [kpa], [kEk], scale=-1.0)
        k.act(Es[:], pa[:, 128:256], AF.Exp, [kpa], [kEs])
        yield
        for h in range(2):
            k.stt(qi[:, h, :], q_[:], rm[:, h:h + 1], Eq[:], ALU.mult, ALU.mult, [kq, kEq, 'rm'], [kqi])
        k.tt('pool', ki[:], kT_[:], Ek[:], ALU.mult, [kkT, kEk], [kki])
        k.tt('pool', ks[:], kt_[:], Es[:], ALU.mult, [kkt, kEs], [kks])
        k.tt('pool', sg_[:], sg_[:], g_[:], ALU.mult, [ksg, kg], [ksg])
        yield
        for h in range(2):
            hp = slice(h * 64, (h + 1) * 64)
            k.mm(pb[:, h * 128:(h + 1) * 128], ki[:], qi[:, h, :], True, True, [kki, kqi], [kpb])
        yield
        for h in range(2):
            k.tt('dve', scs[h][0][:], pb[:, h * 128:(h + 1) * 128], triu[:], ALU.mult, [kpb, 'triu'], [scs[h][1]])
        yield
        for h in range(2):
            hp = slice(h * 64, (h + 1) * 64)
            k.mm(pc[:, h * 128:(h + 1) * 128], scs[h][0][:], vr_[:, h * 128:(h + 1) * 128], True, False, [scs[h][1], kvr], [kpc])
            k.mm(pc[:, h * 128:(h + 1) * 128], qi[:, h, :], S[:], False, True, [kqi, 'S'], [kpc])
        k.mm(pc[:, 256:512], ks[:], vr_[:], True, True, [kks, kvr], [kpc])
        yield
        for h in range(2):
            hp = slice(h * 64, (h + 1) * 64)
            k.stt(S[hp, :], S[hp, :].bitcast(F32), Eq[hp, 127:128], pc[hp, 256 + h * 128:256 + (h + 1) * 128], ALU.mult, ALU.add,
                  ['S', kEq, kpc], ['S'])
        k.cp('act', orw[:], pc[:, 0:256], [kpc], [korw])
        yield
        for h in range(2):
            k.act(junk[:], orw[:, h * 128:(h + 1) * 128], AF.Square, [korw], ['junk', kss], accum_out=ss_[:, h:h + 1])
        yield
        k.ts('dve', rs_[:], ss_[:], 1.0 / 128.0, EPS, ALU.mult, ALU.add, [kss], [krs])
        yield
        k.act(rs_[:], rs_[:], AF.Ln, [krs], [krs])
        k.act(rs_[:], rs_[:], AF.Exp, [krs], [krs], scale=-0.5)
        yield
        for h in range(2):
            hs = slice(h * 128, (h + 1) * 128)
            k.stt(ob_[:, hs], orw[:, hs], rs_[:, h:h + 1], gnbc[:, hs], ALU.mult, ALU.mult, [korw, krs, 'gnbc'], [kob])
        yield
        k.tt('pool', ot_[:], ob_[:], sg_[:], ALU.mult, [kob, ksg], [kot])
        k.dma('pool', oa[rows, :], ot_[:], r=[kot], final=True)

    yield from pipeline_gen(tile, NT)


def build_GLA(L, k=None):
    k = k or K()
    for _ in gen_GLA(L, k):
        pass
    return k.finish()


TWO_PI = 2.0 * math.pi
C1 = 6.28125
C2 = TWO_PI - 6.28125
PI_LO = 3.1415925


def range_sincos(k, x, xkey, shape, s_out, c_out, skey, ckey, pfx):
    if not hasattr(k, 'rr_cache'):
        k.rr_cache = {}
    if pfx not in k.rr_cache:
        k.rr_cache[pfx] = (k.sb(pfx + "kf", shape), k.sb(pfx + "ki", shape, I32), k.sb(pfx + "r", shape), k.sb(pfx + "m", shape))
    kf, ki, r, m = k.rr_cache[pfx]
    a = lambda t: t[:]
    K1, K2, K3, K4 = pfx + 'kf', pfx + 'ki', pfx + 'r', pfx + 'm'
    k.ts('dve', a(kf), x, 1.0 / TWO_PI, None, ALU.mult, None, [xkey], [K1])
    k.cp('dve', a(ki), a(kf), [K1], [K2])
    k.cp('dve', a(kf), a(ki), [K2], [K1])
    k.stt(a(r), a(kf), -C1, x, ALU.mult, ALU.add, [K1, xkey], [K3])
    k.stt(a(r), a(kf), -C2, a(r), ALU.mult, ALU.add, [K1, K3], [K3])
    k.ts('dve', a(m), a(r), math.pi, -TWO_PI, ALU.is_gt, ALU.mult, [K3], [K4])
    k.tt('dve', a(r), a(r), a(m), ALU.add, [K3, K4], [K3])
    k.ts('dve', a(m), a(r), -math.pi, TWO_PI, ALU.is_lt, ALU.mult, [K3], [K4])
    k.tt('dve', a(r), a(r), a(m), ALU.add, [K3, K4], [K3])
    k.ts('dve', a(kf), a(r), PI_LO, -PI_LO, ALU.min, ALU.max, [K3], [K1])
    k.act(s_out, a(kf), AF.Sin, [K1], [skey])
    k.ts('dve', a(r), a(r), math.pi / 2, None, ALU.add, None, [K3], [K3])
    k.ts('dve', a(m), a(r), math.pi, -TWO_PI, ALU.is_gt, ALU.mult, [K3], [K4])
    k.tt('dve', a(r), a(r), a(m), ALU.add, [K3, K4], [K3])
    k.ts('dve', a(kf), a(r), PI_LO, -PI_LO, ALU.min, ALU.max, [K3], [K1])
    k.act(c_out, a(kf), AF.Sin, [K1], [ckey])


def gen_S5(L, k):
    NT = L // 128
    NS = 1024
    uT = k.din("uT", [256, L])
    u = k.din("u", [L, 256])
    lam_re = k.din("lam_re", [NS])
    lam_im = k.din("lam_im", [NS])
    lstep = k.din("lstep", [NS])
    Bre = k.din("Bre", [2, 128, 512])
    Bim = k.din("Bim", [2, 128, 512])
    Cre = k.din("Cre", [8, 128, 32])
    Cim = k.din("Cim", [8, 128, 32])
    dsk = k.din("dsk", [256])
    triu_d = k.din("triu", [128, 128])
    iop_d = k.din("iota_p", [128, 1])
    iof_d = k.din("iota_f", [128, 128])
    y = k.dout("y", [L, 256])

    k.push_scope([("triu_s", [128, 128], F32), ("dbc", [128, 256], F32), ("BBr", [128, 2, 512], mybir.dt.float32r), ("BBi", [128, 2, 512], mybir.dt.float32r),
                  ("Pr", [128, NS], F32), ("Pi", [128, NS], F32), ("Qr", [128, 8, 128], F32), ("Qi", [128, 8, 128], F32),
                  ("L128r", [128, 8], F32), ("L128i", [128, 8], F32), ("Cr", [128, 8, 32], F32), ("nCi", [128, 8, 32], F32),
                  ("car_r", [128, 8], F32), ("car_i", [128, 8], F32), ("ntriu", [128, 128], mybir.dt.float32r), ("nCr", [128, 8, 32], mybir.dt.float32r), ("triur", [128, 128], mybir.dt.float32r), ("Crr", [128, 8, 32], mybir.dt.float32r), ("nCir", [128, 8, 32], mybir.dt.float32r)])
    triu = k.sb("triu_s", [128, 128])
    k.dma('sp', triu[:], triu_d, w=['triu'])
    iop = k.sb("iop", [128, 1])
    k.dma('sp', iop[:], iop_d, w=['iop'])
    negp = k.sb("negp", [128, 1])
    k.ts('dve', negp[:], iop[:], -1.0, None, ALU.mult, None, ['iop'], ['negp'])
    iof = k.sb("iof", [128, 128])
    k.dma('sp', iof[:], iof_d, w=['iof'])
    dbc = k.bcast_row("dbc", dsk, 256)
    R = [128, NS]
    lr = k.bcast_row("lr", lam_re, NS)
    li = k.bcast_row("li", lam_im, NS)
    dl = k.bcast_row("dl", lstep, NS)
    k.ts('dve', lr[:], lr[:], -1e-4, None, ALU.min, None, ['lr'], ['lr'])
    k.act(dl[:], dl[:], AF.Exp, ['dl'], ['dl'])
    a_ = k.sb("a_", R)
    th = k.sb("th", R)
    k.tt('dve', a_[:], lr[:], dl[:], ALU.mult, ['lr', 'dl'], ['a_'])
    k.tt('dve', th[:], li[:], dl[:], ALU.mult, ['li', 'dl'], ['th'])
    sn = k.sb("sn", R)
    cs = k.sb("cs", R)
    range_sincos(k, th[:], 'th', R, sn[:], cs[:], 'sn', 'cs', 'rr_')
    ea = k.sb("ea", R)
    k.act(ea[:], a_[:], AF.Exp, ['a_'], ['ea'])
    nr = k.sb("nr", R)
    ni = k.sb("ni", R)
    k.tt('dve', nr[:], ea[:], cs[:], ALU.mult, ['ea', 'cs'], ['nr'])
    k.ts('dve', nr[:], nr[:], -1.0, None, ALU.add, None, ['nr'], ['nr'])
    k.tt('dve', ni[:], ea[:], sn[:], ALU.mult, ['ea', 'sn'], ['ni'])
    den = k.sb("den", R)
    t0 = k.sb("t0", R)
    k.tt('dve', den[:], lr[:], lr[:], ALU.mult, ['lr'], ['den'])
    k.tt('dve', t0[:], li[:], li[:], ALU.mult, ['li'], ['t0'])
    k.tt('dve', den[:], den[:], t0[:], ALU.add, ['den', 't0'], ['den'])
    k.recip(den[:], den[:], ['den'], ['den'])
    gr = k.sb("gr", R)
    gi = k.sb("gi", R)
    k.tt('dve', gr[:], nr[:], lr[:], ALU.mult, ['nr', 'lr'], ['gr'])
    k.tt('dve', t0[:], ni[:], li[:], ALU.mult, ['ni', 'li'], ['t0'])
    k.tt('dve', gr[:], gr[:], t0[:], ALU.add, ['gr', 't0'], ['gr'])
    k.tt('dve', gr[:], gr[:], den[:], ALU.mult, ['gr', 'den'], ['gr'])
    k.tt('dve', gi[:], ni[:], lr[:], ALU.mult, ['ni', 'lr'], ['gi'])
    k.tt('dve', t0[:], nr[:], li[:], ALU.mult, ['nr', 'li'], ['t0'])
    k.tt('dve', gi[:], gi[:], t0[:], ALU.subtract, ['gi', 't0'], ['gi'])
    k.tt('dve', gi[:], gi[:], den[:], ALU.mult, ['gi', 'den'], ['gi'])
    Br = k.sb("Br", [128, 2, 512])
    Bi = k.sb("Bi", [128, 2, 512])
    BBr = k.sb("BBr", [128, 2, 512])
    BBi = k.sb("BBi", [128, 2, 512])
    for hc in range(2):
        k.dma('sp', Br[:, hc, :], Bre[hc], w=[f'Br{hc}'])
        k.dma('sp', Bi[:, hc, :], Bim[hc], w=[f'Bi{hc}'])
    grv = gr[:].rearrange("p (h n) -> p h n", h=2)
    giv = gi[:].rearrange("p (h n) -> p h n", h=2)
    t0v = t0[:].rearrange("p (h n) -> p h n", h=2)
    BK = ['Br0', 'Br1', 'Bi0', 'Bi1']
    k.tt('dve', BBr[:], grv, Br[:], ALU.mult, ['gr'] + BK, ['BBr'])
    k.tt('dve', t0v, giv, Bi[:], ALU.mult, ['gi'] + BK, ['t0'])
    k.tt('dve', BBr[:], BBr[:].bitcast(F32), t0v, ALU.subtract, ['BBr', 't0'], ['BBr'])
    k.tt('dve', BBi[:], grv, Bi[:], ALU.mult, ['gr'] + BK, ['BBi'])
    k.tt('dve', t0v, giv, Br[:], ALU.mult, ['gi'] + BK, ['t0'])
    k.tt('dve', BBi[:], BBi[:].bitcast(F32), t0v, ALU.add, ['BBi', 't0'], ['BBi'])
    ang = k.sb("ang", R)
    k.ts('dve', ang[:], th[:], iop[:, 0:1], None, ALU.mult, None, ['th', 'iop'], ['ang'])
    Pr = k.sb("Pr", R)
    Pi = k.sb("Pi", R)
    range_sincos(k, ang[:], 'ang', R, sn[:], cs[:], 'sn', 'cs', 'rr_')
    k.act(ea[:], a_[:], AF.Exp, ['a_', 'negp'], ['ea'], scale=negp[:, 0:1])
    k.tt('dve', Pr[:], ea[:], cs[:], ALU.mult, ['ea', 'cs'], ['Pr'])
    k.stt(Pi[:], ea[:], -1.0, sn[:], ALU.mult, ALU.mult, ['ea', 'sn'], ['Pi'])
    Cs = [128, 8]
    lrc = k.sb("lrc", Cs)
    lic = k.sb("lic", Cs)
    dlc = k.sb("dlc", Cs)
    cv = lambda d: d.rearrange("(blk p) -> p blk", p=128)
    k.dma('sp', lrc[:], cv(lam_re), w=['lrc'], allow_slow_non_contiguous=True)
    k.dma('sp', lic[:], cv(lam_im), w=['lic'], allow_slow_non_contiguous=True)
    k.dma('sp', dlc[:], cv(lstep), w=['dlc'], allow_slow_non_contiguous=True)
    k.ts('dve', lrc[:], lrc[:], -1e-4, None, ALU.min, None, ['lrc'], ['lrc'])
    k.act(dlc[:], dlc[:], AF.Exp, ['dlc'], ['dlc'])
    ac = k.sb("ac", Cs)
    thc = k.sb("thc", Cs)
    k.tt('dve', ac[:], lrc[:], dlc[:], ALU.mult, ['lrc', 'dlc'], ['ac'])
    k.tt('dve', thc[:], lic[:], dlc[:], ALU.mult, ['lic', 'dlc'], ['thc'])
    Qr = k.sb("Qr", [128, 8, 128])
    Qi = k.sb("Qi", [128, 8, 128])
    angv = ang[:].rearrange("p (b t) -> p b t", b=8)
    eav = ea[:].rearrange("p (b t) -> p b t", b=8)
    for blk in range(8):
        k.ts('dve', angv[:, blk, :], iof[:], thc[:, blk:blk + 1], None, ALU.mult, None, ['iof', 'thc'], ['ang'])
    range_sincos(k, ang[:], 'ang', R, sn[:], cs[:], 'sn', 'cs', 'rr_')
    for blk in range(8):
        k.act(eav[:, blk, :], iof[:], AF.Exp, ['iof', 'ac'], ['ea'], scale=ac[:, blk:blk + 1])
    k.tt('dve', Qr[:].rearrange("p b t -> p (b t)"), ea[:], cs[:], ALU.mult, ['ea', 'cs'], ['Qr'])
    k.tt('dve', Qi[:].rearrange("p b t -> p (b t)"), ea[:], sn[:], ALU.mult, ['ea', 'sn'], ['Qi'])
    a128 = k.sb("a128", Cs)
    s128 = k.sb("s128", Cs)
    c128 = k.sb("c128", Cs)
    L128r = k.sb("L128r", Cs)
    L128i = k.sb("L128i", Cs)
    k.ts('dve', a128[:], thc[:], 128.0, None, ALU.mult, None, ['thc'], ['a128'])
    range_sincos(k, a128[:], 'a128', Cs, s128[:], c128[:], 's128', 'c128', 'rc_')
    k.act(a128[:], ac[:], AF.Exp, ['ac', 's128', 'c128'], ['a128'], scale=128.0)
    k.tt('dve', L128r[:], a128[:], c128[:], ALU.mult, ['a128', 'c128'], ['L128r'])
    k.tt('dve', L128i[:], a128[:], s128[:], ALU.mult, ['a128', 's128'], ['L128i'])
    Cr = k.sb("Cr", [128, 8, 32])
    nCi = k.sb("nCi", [128, 8, 32])
    k.dma('sp', Cr[:], Cre.rearrange("b p c -> p b c"), w=['Cr'])
    k.dma('sp', nCi[:], Cim.rearrange("b p c -> p b c"), w=['nCi'])
    k.ts('dve', nCi[:], nCi[:], -1.0, None, ALU.mult, None, ['nCi'], ['nCi'])
    car_r = k.sb("car_r", Cs)
    car_i = k.sb("car_i", Cs)
    k.memset('dve', car_r[:], 0.0, ['car_r0', 'car_r1'])
    k.memset('dve', car_i[:], 0.0, ['car_i0', 'car_i1'])
    ntriu = k.sb("ntriu", [128, 128])
    k.ts('dve', ntriu[:], triu[:], -1.0, None, ALU.mult, None, ['triu'], ['ntriu'])
    nCr = k.sb("nCr", [128, 8, 32])
    k.ts('dve', nCr[:], Cr[:], -1.0, None, ALU.mult, None, ['Cr'], ['nCr'])
    triur = k.sb("triur", [128, 128])
    k.cp('dve', triur[:], triu[:], ['triu'], ['triur'])
    Crr = k.sb("Crr", [128, 8, 32])
    k.cp('dve', Crr[:], Cr[:], ['Cr'], ['Crr'])
    nCir = k.sb("nCir", [128, 8, 32])
    k.cp('dve', nCir[:], nCi[:], ['nCi'], ['nCir'])
    k.pop_scope()
    if hasattr(k, 'rr_cache'):
        del k.rr_cache
    def ring(nm, shape, n, dt=F32):
        return [k.sb(f"{nm}{j}", shape, dt) for j in range(n)]
    FR_ = mybir.dt.float32r
    uTt = ring("uTt", [128, 128], 3)
    uTr = ring("uTr", [128, 128], 3, FR_)
    ut = ring("ut", [128, 128], 5)
    yo = ring("yo", [128, 128], 9)
    m1, m2, m3, m4 = ring("m1_", [128, 512], 3, FR_), ring("m2_", [128, 512], 3, FR_), ring("m3_", [128, 512], 3, FR_), ring("m4_", [128, 512], 3, FR_)
    Xtr, Xti = ring("Xtr", [128, 512], 3), ring("Xti", [128, 512], 3)
    Gr, Gi = ring("Gr", [128, 4, 128], 3), ring("Gi", [128, 4, 128], 3)
    n1, n2, n3, n4 = ring("n1_", [128, 512], 3, FR_), ring("n2_", [128, 512], 3, FR_), ring("n3_", [128, 512], 3, FR_), ring("n4_", [128, 512], 3, FR_)
    Hr, Hi = ring("Hr", [128, 4, 128], 3), ring("Hi", [128, 4, 128], 3)
    cc1 = [k.sb(f"cc1_{h}", [128, 4]) for h in range(2)]
    cc2 = [k.sb(f"cc2_{h}", [128, 4]) for h in range(2)]
    psXr = k.ps("psXr", [128, 512])
    psXi = k.ps("psXi", [128, 512])
    psGr = k.ps("psGr", [128, 512])
    psGi = k.ps("psGi", [128, 512])
    psY = k.ps("psY", [128, 512])
    fl = lambda t: t[:].rearrange("p b t -> p (b t)")

    def item(j):
        i, hc = divmod(j, 2)
        rows = slice(i * 128, (i + 1) * 128)
        cs_ = slice(hc * 512, (hc + 1) * 512)
        bs = slice(hc * 4, (hc + 1) * 4)
        def T(lst, nm):
            q = j % len(lst)
            return lst[q], f'{nm}{q}'
        uT_, kuT = T(uTt, 'uTt'); uR_, kuR = T(uTr, 'uTr'); ut_, kut = T(ut, 'ut'); yo_, kyo = T(yo, 'yo')
        m1_, km1 = T(m1, 'm1'); m2_, km2 = T(m2, 'm2'); m3_, km3 = T(m3, 'm3'); m4_, km4 = T(m4, 'm4')
        Xr_, kXr = T(Xtr, 'Xtr'); Xi_, kXi = T(Xti, 'Xti'); Gr_, kGr = T(Gr, 'Gr'); Gi_, kGi = T(Gi, 'Gi')
        n1_, kn1 = T(n1, 'n1'); n2_, kn2 = T(n2, 'n2'); n3_, kn3 = T(n3, 'n3'); n4_, kn4 = T(n4, 'n4')
        Hr_, kHr = T(Hr, 'Hr'); Hi_, kHi = T(Hi, 'Hi')
        k.dma('sp', uT_[:], uT[hc * 128:(hc + 1) * 128, rows], w=[kuT])
        k.dma('sp', ut_[:], u[rows, hc * 128:(hc + 1) * 128], w=[kut])
        yield
        k.cp('act', uR_[:], uT_[:], [kuT], [kuR])
        yield
        k.mm(psXr[:], uR_[:], BBr[:, hc, :], True, True, [kuR, 'BBr'], ['psXr'])
        k.mm(psXi[:], uR_[:], BBi[:, hc, :], True, True, [kuR, 'BBi'], ['psXi'])
        yield
        k.tt('dve', m1_[:], psXr[:], Pr[:, cs_], ALU.mult, ['psXr', 'Pr'], [km1])
        k.tt('dve', m3_[:], psXr[:], Pi[:, cs_], ALU.mult, ['psXr', 'Pi'], [km3])
        k.tt('dve', m2_[:], psXi[:], Pi[:, cs_], ALU.mult, ['psXi', 'Pi'], [km2])
        k.tt('dve', m4_[:], psXi[:], Pr[:, cs_], ALU.mult, ['psXi', 'Pr'], [km4])
        yield
        k.tt('pool', yo_[:], ut_[:], dbc[:, hc * 128:(hc + 1) * 128], ALU.mult, [kut, 'dbc'], [kyo])
        yield
        for nb in range(4):
            ns = slice(nb * 128, (nb + 1) * 128)
            k.mm(psGr[:, ns], m1_[:, ns], triur[:], True, False, [km1, 'triur'], ['psGr'])
            k.mm(psGr[:, ns], m2_[:, ns], ntriu[:], False, True, [km2, 'ntriu'], ['psGr'])
            k.mm(psGi[:, ns], m3_[:, ns], triur[:], True, False, [km3, 'triur'], ['psGi'])
            k.mm(psGi[:, ns], m4_[:, ns], triur[:], False, True, [km4, 'triur'], ['psGi'])
        yield
        k.tt('dve', Gr_[:], psGr[:].rearrange("p (b t) -> p b t", b=4),
             car_r[:, bs].unsqueeze(2).broadcast_to([128, 4, 128]), ALU.add, ['psGr', f'car_r{hc}'], [kGr])
        k.tt('dve', Gi_[:], psGi[:].rearrange("p (b t) -> p b t", b=4),
             car_i[:, bs].unsqueeze(2).broadcast_to([128, 4, 128]), ALU.add, ['psGi', f'car_i{hc}'], [kGi])
        gr127 = Gr_[:, :, 127]
        gi127 = Gi_[:, :, 127]
        CK = [f'cc1{hc}', f'cc2{hc}']
        k.tt('dve', cc1[hc][:], L128r[:, bs], gr127, ALU.mult, ['L128r', kGr], [CK[0]])
        k.tt('dve', cc2[hc][:], L128i[:, bs], gi127, ALU.mult, ['L128i', kGi], [CK[1]])
        k.tt('dve', car_r[:, bs], cc1[hc][:], cc2[hc][:], ALU.subtract, CK, [f'car_r{hc}'])
        k.tt('dve', cc1[hc][:], L128r[:, bs], gi127, ALU.mult, ['L128r', kGi], [CK[0]])
        k.tt('dve', cc2[hc][:], L128i[:, bs], gr127, ALU.mult, ['L128i', kGr], [CK[1]])
        k.tt('dve', car_i[:, bs], cc1[hc][:], cc2[hc][:], ALU.add, CK, [f'car_i{hc}'])
        yield
        qr = Qr[:, bs, :].rearrange("p b t -> p (b t)")
        qi = Qi[:, bs, :].rearrange("p b t -> p (b t)")
        k.tt('dve', n1_[:], fl(Gr_), qr, ALU.mult, [kGr, 'Qr'], [kn1])
        k.tt('dve', n2_[:], fl(Gi_), qi, ALU.mult, [kGi, 'Qi'], [kn2])
        k.tt('dve', n3_[:], fl(Gi_), qr, ALU.mult, [kGi, 'Qr'], [kn3])
        k.tt('dve', n4_[:], fl(Gr_), qi, ALU.mult, [kGr, 'Qi'], [kn4])
        yield
        for nb in range(4):
            blk = hc * 4 + nb
            ns = slice(nb * 128, (nb + 1) * 128)
            yo_s = psY[:, blk * 32:(blk + 1) * 32]
            k.mm(yo_s, n1_[:, ns], Crr[:, blk, :], True, False, [kn1, 'Crr'], ['psY'])
            k.mm(yo_s, n2_[:, ns], nCr[:, blk, :], False, False, [kn2, 'nCr'], ['psY'])
            k.mm(yo_s, n3_[:, ns], nCir[:, blk, :], False, False, [kn3, 'nCir'], ['psY'])
            k.mm(yo_s, n4_[:, ns], nCir[:, blk, :], False, True, [kn4, 'nCir'], ['psY'])
        yield
        k.tt('dve', yo_[:], yo_[:], psY[:, hc * 128:(hc + 1) * 128], ALU.add, [kyo, 'psY'], [kyo])
        yield
        k.dma('pool', y[rows, hc * 128:(hc + 1) * 128], yo_[:], r=[kyo], final=True)

    yield from pipeline_gen(item, 2 * NT)


def build_S5(L, k=None):
    k = k or K()
    for _ in gen_S5(L, k):
        pass
    return k.finish()


def s5_host_inputs(s, proj_u, prm):
    gs = slice(16 * s, 16 * s + 16)
    cs = slice(256 * s, 256 * s + 256)
    uc = np.ascontiguousarray(proj_u[:, cs])
    Bre = np.zeros((2, 128, 512), np.float32)
    Bim = np.zeros((2, 128, 512), np.float32)
    Cre = np.zeros((8, 128, 32), np.float32)
    Cim = np.zeros((8, 128, 32), np.float32)
    b_re, b_im = prm['s5_b_re'][gs], prm['s5_b_im'][gs]
    c_re, c_im = prm['s5_c_re'][gs], prm['s5_c_im'][gs]
    for g in range(16):
        hc, gl = g // 8, g % 8
        Bre[hc, gl * 16:(gl + 1) * 16, gl * 64:(gl + 1) * 64] = b_re[g].T
        Bim[hc, gl * 16:(gl + 1) * 16, gl * 64:(gl + 1) * 64] = b_im[g].T
        blk, g2 = g // 2, g % 2
        Cre[blk, g2 * 64:(g2 + 1) * 64, g2 * 16:(g2 + 1) * 16] = c_re[g].T
        Cim[blk, g2 * 64:(g2 + 1) * 64, g2 * 16:(g2 + 1) * 16] = c_im[g].T
    return dict(uT=np.ascontiguousarray(uc.T), u=uc,
                lam_re=np.ascontiguousarray(prm['s5_lambda_re'][gs].reshape(-1)),
                lam_im=np.ascontiguousarray(prm['s5_lambda_im'][gs].reshape(-1)),
                lstep=np.ascontiguousarray(np.repeat(prm['s5_log_step'][gs], 64)),
                Bre=Bre, Bim=Bim, Cre=Cre, Cim=Cim, dsk=np.ascontiguousarray(prm['s5_d'][cs]),
                triu=np.triu(np.ones((128, 128), np.float32)),
                iota_p=np.arange(128, dtype=np.float32).reshape(128, 1),
                iota_f=np.tile(np.arange(128, dtype=np.float32)[None], (128, 1)))


GELU_C = 1.5957691216057308


def gen_LRU(L, k):
    TT = 512
    NCH = L // TT
    xbT = k.din("xbT", [256, L])
    gateT = k.din("gateT", [256, L])
    cw_d = k.din("cw", [128, 2, 4])
    cb_d = k.din("cb", [128, 2])
    Wa_d = k.din("Wa", [2, 128, 128])
    Wx_d = k.din("Wx", [2, 128, 128])
    ba_d = k.din("ba", [128, 2])
    bx_d = k.din("bx", [128, 2])
    lam_d = k.din("lam", [128, 2])
    odT = k.dout("odT", [256, L])
    cw = k.sb("cw_s", [128, 2, 4])
    cb = k.sb("cb_s", [128, 2])
    Wa = k.sb("Wa_s", [128, 2, 128])
    Wx = k.sb("Wx_s", [128, 2, 128])
    ba = k.sb("ba_s", [128, 2])
    bx = k.sb("bx_s", [128, 2])
    c8 = k.sb("c8", [128, 2])
    k.dma('sp', cw[:], cw_d, w=['cw'])
    k.dma('sp', cb[:], cb_d, w=['cb'])
    k.dma('sp', Wa[:], Wa_d.rearrange("b p n -> p b n"), w=['Wa'])
    k.dma('sp', Wx[:], Wx_d.rearrange("b p n -> p b n"), w=['Wx'])
    k.dma('sp', ba[:], ba_d, w=['ba'])
    k.dma('sp', bx[:], bx_d, w=['bx'])
    k.dma('sp', c8[:], lam_d, w=['c8'])
    k.act(c8[:], c8[:], AF.Exp, ['c8'], ['c8'], scale=-1.0)
    k.act(c8[:], c8[:], AF.Ln, ['c8'], ['c8'], bias=1.0)
    k.ts('dve', c8[:], c8[:], -8.0, None, ALU.mult, None, ['c8'], ['c8'])
    hlast = k.sb("hlast", [128, 2])
    k.memset('dve', hlast[:], 0.0, ['hlast0', 'hlast1'])
    xh = [k.sb(f"xh{i}", [128, TT + 3]) for i in range(2)]
    gt = [k.sb(f"gt{i}", [128, TT]) for i in range(2)]
    xc = k.sb("xc", [128, TT])
    r = k.sb("r", [128, TT])
    ig = k.sb("ig", [128, TT])
    a = k.sb("a", [128, TT])
    a2 = k.sb("a2", [128, TT])
    bt = k.sb("bt", [128, TT])
    h = k.sb("h", [128, TT])
    g2 = k.sb("g2", [128, TT])
    ge = k.sb("ge", [128, TT])
    ot = [k.sb(f"ot{i}", [128, TT]) for i in range(2)]
    psR = k.ps("psR", [128, TT])
    psI = k.ps("psI", [128, TT])
    n = 0
    for c in range(NCH):
        for pb in range(2):
            b = n % 2
            n += 1
            prow = slice(pb * 128, (pb + 1) * 128)
            if c == 0:
                k.memset('pool', xh[b][:, 0:3], 0.0, [f'xh{b}h'])
                k.dma('sp', xh[b][:, 3:TT + 3], xbT[prow, 0:TT], w=[f'xh{b}'])
            else:
                k.dma('sp', xh[b][:, 0:TT + 3], xbT[prow, c * TT - 3:(c + 1) * TT], w=[f'xh{b}', f'xh{b}h'])
            k.dma('sp', gt[b][:], gateT[prow, c * TT:(c + 1) * TT], w=[f'gt{b}'])
            xk = [f'xh{b}', f'xh{b}h']
            k.ts('dve', xc[:], xh[b][:, 3:TT + 3], cw[:, pb, 3:4], cb[:, pb:pb + 1], ALU.mult, ALU.add, xk + ['cw', 'cb'], ['xc'])
            for j in (2, 1, 0):
                k.stt(xc[:], xh[b][:, j:j + TT], cw[:, pb, j:j + 1], xc[:], ALU.mult, ALU.add, xk + ['cw', 'xc'], ['xc'])
            k.mm(psR[:], Wa[:, pb, :], xc[:], True, True, ['Wa', 'xc'], ['psR'])
            k.mm(psI[:], Wx[:, pb, :], xc[:], True, True, ['Wx', 'xc'], ['psI'])
            k.act(r[:], psR[:], AF.Sigmoid, ['psR', 'ba'], ['r'], bias=ba[:, pb:pb + 1])
            k.act(ig[:], psI[:], AF.Sigmoid, ['psI', 'bx'], ['ig'], bias=bx[:, pb:pb + 1])
            k.act(a[:], r[:], AF.Exp, ['r', 'c8'], ['a'], scale=c8[:, pb:pb + 1])
            k.act(a2[:], a[:], AF.Square, ['a'], ['a2'])
            k.act(a2[:], a2[:], AF.Sqrt, ['a2'], ['a2'], scale=-1.0, bias=1.0)
            k.tt('dve', bt[:], ig[:], xc[:], ALU.mult, ['ig', 'xc'], ['bt'])
            k.tt('dve', bt[:], bt[:], a2[:], ALU.mult, ['bt', 'a2'], ['bt'])
            k.P.op('dve', lambda e, pb=pb: e.tensor_tensor_scan(out=h[:], data0=a[:], data1=bt[:], initial=hlast[:, pb:pb + 1],
                                                                op0=ALU.mult, op1=ALU.add),
                   reads=['a', 'bt', f'hlast{pb}'], writes=['h'])
            k.cp('dve', hlast[:, pb:pb + 1], h[:, TT - 1:TT], ['h'], [f'hlast{pb}'])
            k.act(g2[:], gt[b][:], AF.Square, [f'gt{b}'], ['g2'])
            k.act(g2[:], g2[:], AF.Copy, ['g2'], ['g2'], scale=0.044715, bias=1.0)
            k.tt('dve', g2[:], g2[:], gt[b][:], ALU.mult, ['g2', f'gt{b}'], ['g2'])
            k.act(g2[:], g2[:], AF.Sigmoid, ['g2'], ['g2'], scale=GELU_C)
            k.tt('dve', ge[:], g2[:], gt[b][:], ALU.mult, ['g2', f'gt{b}'], ['ge'])
            k.tt('dve', ot[b][:], h[:], ge[:], ALU.mult, ['h', 'ge'], [f'ot{b}'])
            k.dma('pool', odT[prow, c * TT:(c + 1) * TT], ot[b][:], r=[f'ot{b}'], final=True)
            yield


def build_LRU(L, k=None):
    k = k or K()
    for _ in gen_LRU(L, k):
        pass
    return k.finish()


def lru_host_inputs(s, xb, gate, prm):
    cs = slice(256 * s, 256 * s + 256)
    col = lambda v: np.ascontiguousarray(v[cs].reshape(2, 128).T)
    Wa = np.zeros((2, 128, 128), np.float32)
    Wx = np.zeros((2, 128, 128), np.float32)
    for pb in range(2):
        for bl in range(2):
            blk = 4 * s + 2 * pb + bl
            Wa[pb, bl * 64:(bl + 1) * 64, bl * 64:(bl + 1) * 64] = prm['lru_w_a'][blk]
            Wx[pb, bl * 64:(bl + 1) * 64, bl * 64:(bl + 1) * 64] = prm['lru_w_x'][blk]
    cw = np.ascontiguousarray(prm['lru_conv_w'][:, cs].reshape(4, 2, 128).transpose(2, 1, 0))
    return dict(xbT=np.ascontiguousarray(xb[:, cs].T), gateT=np.ascontiguousarray(gate[:, cs].T), cw=cw,
                cb=col(prm['lru_conv_b']), Wa=Wa, Wx=Wx, ba=col(prm['lru_b_a']), bx=col(prm['lru_b_x']),
                lam=col(prm['lru_lambda']))


GN_EPS = 64e-5
NLEV = 5


def build_RWKV(L, k=None, NH=4, fr=False, CH=64):
    k = k or K()
    NT = L // 128
    W = NH * 64
    NG = NH // 4
    FR = mybir.dt.float32r if fr else F32
    rd = (lambda ap: ap.bitcast(F32)) if fr else (lambda ap: ap)
    NCK = 128 // CH
    nlev = 5 if CH == 64 else 6
    frc = fr and CH == 128
    FRC = mybir.dt.float32r if frc else F32
    rdc = (lambda ap: ap.bitcast(F32)) if frc else (lambda ap: ap)
    lhc = (lambda ap: ap) if frc else rd
    prkv = [k.din(nm, [L, W]) for nm in ("pr", "pk", "pv")]
    mu1 = k.din("mu1", [3 * W])
    pls = [k.din("plw", [64, L]), k.din("pla", [64, L]), k.din("plg", [128, L])]
    mul = k.din("mul", [128, 3])
    w2 = k.din("w2", [64, W])
    a2 = k.din("a2", [64, W])
    g2 = k.din("g2", [128, W])
    vecs = k.din("vecs", [7, W])
    ident_d = k.din("ident", [128, 128])
    triw_d = k.din("triw", [3, 128, 128])
    mask5_d = k.din("mask5", [128, 640])
    rowm_d = k.din("rowm", [128, 2])
    oc = k.dout("oc", [L, W])

    k.consts(ident_d)
    triw = k.sb("triw_s", [128, 3, 128])
    k.dma('sp', triw[:], triw_d.rearrange("a p n -> p a n"), w=['triw'])
    mask5 = k.sb("mask5_s", [128, 640])
    k.dma('sp', mask5[:], mask5_d, w=['mask5'])
    rowm = k.sb("rowm_s", [128, 2])
    k.dma('sp', rowm[:], rowm_d, w=['rowm'])
    mu1bc = k.bcast_row("mu1bc", mu1, 3 * W)
    vb = [k.bcast_row(f"vb{i}", vecs[i], W) for i in range(7)]
    w0bc, a0bc, kkbc, kabc, rkbc, lngbc, lnbbc = vb
    VK = [f"vb{i}" for i in range(7)]
    muls = k.sb("muls", [128, 3])
    k.dma('sp', muls[:], mul, w=['muls'])
    w2s = k.sb("w2s", [64, W])
    a2s = k.sb("a2s", [64, W])
    k.dma('sp', w2s[:], w2, w=['w2s'])
    k.dma('sp', a2s[:], a2, w=['a2s'])
    g2s = k.sb("g2s", [128, W])
    k.dma('sp', g2s[:], g2, w=['g2s'])
    ST = [k.sb(f"ST{i}", [64, 64], FRC) for i in range(NH)]
    zt = k.sb("zt", [128, W])
    k.memset('dve', zt[:], 0.0, ['zt'])
    for i in range(NH):
        k.cp('dve', ST[i][:], zt[0:64, 0:64], ['zt'], [f'ST{i}'])
    P1s = k.sb("P1s", [128, W], FRC)
    Us = k.sb("Us", [128, W], FRC)
    k.cp('dve', P1s[:], zt[:], ['zt'], ['P1s'])
    k.cp('dve', Us[:], zt[:], ['zt'], ['Us'])

    pt = [k.sb(f"pt{i}", [128, 3 * W]) for i in range(2)]
    pp = [k.sb(f"pp{i}", [128, 3 * W]) for i in range(2)]
    lt = [k.sb(f"lt{i}", [128, 3, 128]) for i in range(2)]
    lp = [k.sb(f"lp{i}", [128, 3, 128]) for i in range(2)]
    for i_ in range(2):
        k.memset('pool', lt[i_][:], 0.0, [f'lt{i_}0', f'lt{i_}1', f'lt{i_}2'])
        k.memset('pool', lp[i_][:], 0.0, [f'lp{i_}0', f'lp{i_}1', f'lp{i_}2', f'lp{i_}z'])
    pm = k.sb("pm", [128, 3 * W])
    vr = k.sb("vr", [128, W], FR)
    lm = k.sb("lm", [128, 3, 128])
    sw = k.sb("sw", [128, W])
    av = k.sb("av", [128, W])
    gv = k.sb("gv", [128, W])
    kkr = k.sb("kkr", [128, W])
    sq = k.sb("sq", [128, W])
    s4 = k.sb("s4", [128, NH])
    rn = k.sb("rn", [128, NH])
    nkk = k.sb("nkk", [128, W])
    kmod = k.sb("kmod", [128, W])
    kka = k.sb("kka", [128, W])
    tmp = k.sb("tmp", [128, W])
    bon = k.sb("bon", [128, NH])
    E1 = k.sb("E1", [128, W])
    E2 = k.sb("E2", [128, W])
    E3 = k.sb("E3", [128, W])
    E4 = k.sb("E4", [128, W])
    E1T = k.sb("E1T", [64, NH, 128])
    At = k.sb("At", [128, W])
    Bs = k.sb("Bs", [128, W])
    Ks = k.sb("Ks", [128, W])
    Rt = k.sb("Rt", [128, W])
    Bfm = [k.sb(f"Bfm{c}", [128, W]) for c in range(2)]
    Kfm = [k.sb(f"Kfm{c}", [128, W]) for c in range(2)]
    FT = [k.sb(f"FT{h}", [64, 4, 128], FR) for h in range(NH)]
    A5 = [k.sb(f"A5_{h}", [128, 640], FR) for h in range(NH)]
    NL = [k.sb(f"NL_{h}", [128, 256], FR) for h in range(NH)]
    PQ = [k.sb(f"PQ_{h}", [128, 256], FR) for h in range(NH)]
    W1 = k.sb("W1", [128, W], FR)
    U1 = k.sb("U1", [128, W])
    ysb = k.sb("ysb", [128, W])
    yc = k.sb("yc", [128, W])
    m4 = k.sb("m4", [128, NH])
    r4 = k.sb("r4", [128, NH])
    ot = [k.sb(f"ot{i}", [128, W]) for i in range(2)]
    B = [k.ps(f"psB{i}", [128, 512]) for i in range(8)]
    bk = lambda i: f'psB{i}'
    v3 = lambda t: t.rearrange("p (h j) -> p h j", h=NH)
    bc4 = lambda t: t.unsqueeze(2).broadcast_to([128, NH, 64])

    for i in range(NT):
        b = i % 2
        rows = slice(i * 128, (i + 1) * 128)
        PK, PPK, LTK, LPK = [], [], [], []
        for q in range(3):
            cq = slice(q * W, (q + 1) * W)
            k.dma('sp', pt[b][:, cq], prkv[q][rows, :], w=[f'pt{b}{q}'])
            PK.append(f'pt{b}{q}')
            if i == 0:
                k.dma('sp', pp[b][1:128, cq], prkv[q][0:127, :], w=[f'pp{b}{q}'])
            else:
                k.dma('sp', pp[b][:, cq], prkv[q][i * 128 - 1:i * 128 + 127, :], w=[f'pp{b}{q}'])
            PPK.append(f'pp{b}{q}')
            nr = pls[q].shape[0]
            k.dma('sp', lt[b][0:nr, q, :], pls[q][:, rows], w=[f'lt{b}{q}'])
            LTK.append(f'lt{b}{q}')
            if i == 0:
                k.dma('sp', lp[b][0:nr, q, 1:128], pls[q][:, 0:127], w=[f'lp{b}{q}'])
            else:
                k.dma('sp', lp[b][0:nr, q, :], pls[q][:, i * 128 - 1:i * 128 + 127], w=[f'lp{b}{q}'])
            LPK.append(f'lp{b}{q}')
        if i == 0:
            k.memset('pool', pp[b][0:1, :], 0.0, [f'pp{b}z'])
            k.memset('pool', lp[b][:, :, 0:1], 0.0, [f'lp{b}z'])
            PPK.append(f'pp{b}z')
            LPK.append(f'lp{b}z')
        k.tt('pool', pm[:], pp[b][:], pt[b][:], ALU.subtract, PPK + PK, ['pm'])
        k.tt('pool', pm[:], pm[:], mu1bc[:], ALU.mult, ['pm', 'mu1bc'], ['pm'])
        k.tt('pool', pm[:], pm[:], pt[b][:], ALU.add, ['pm'] + PK, ['pm'])
        r_, k_, v_ = pm[:, 0:W], pm[:, W:2 * W], pm[:, 2 * W:3 * W]
        k.cp('act', vr[:], v_, ['pm'], ['vr'])
        LK = LTK + LPK
        k.tt('dve', lm[:], lp[b][:], lt[b][:], ALU.subtract, LK, ['lm'])
        for blk in range(3):
            k.stt(lm[:, blk, :], lm[:, blk, :], muls[:, blk:blk + 1], lt[b][:, blk, :], ALU.mult, ALU.add,
                  ['lm', 'muls'] + LK, ['lm'])
        k.act(lm[0:64, 0, :], lm[0:64, 0, :], AF.Tanh, ['lm'], ['lm'])
        k.act(lm[:, 2, :], lm[:, 2, :], AF.Sigmoid, ['lm'], ['lm'])
        k.mm(B[0][:, 0:W], lm[0:64, 0, :], w2s[:], True, True, ['lm', 'w2s'], [bk(0)])
        k.mm(B[1][:, 0:W], lm[0:64, 1, :], a2s[:], True, True, ['lm', 'a2s'], [bk(1)])
        k.mm(B[2][:, 0:W], lm[:, 2, :], g2s[:], True, True, ['lm', 'g2s'], [bk(2)])
        k.tt('dve', sw[:], B[0][:, 0:W], w0bc[:], ALU.add, [bk(0), VK[0]], ['sw'])
        k.act(sw[:], sw[:], AF.Sigmoid, ['sw'], ['sw'])
        k.tt('dve', av[:], B[1][:, 0:W], a0bc[:], ALU.add, [bk(1), VK[1]], ['av'])
        k.act(av[:], av[:], AF.Sigmoid, ['av'], ['av'])
        k.cp('act', gv[:], B[2][:, 0:W], [bk(2)], ['gv'])
        k.tt('pool', kkr[:], k_, kkbc[:], ALU.mult, ['pm', VK[2]], ['kkr'])
        k.tt('pool', sq[:], kkr[:], kkr[:], ALU.mult, ['kkr'], ['sq'])
        k.P.op('dve', lambda e: e.tensor_reduce(out=s4[:], in_=v3(sq[:]), axis=AX.X, op=ALU.add), reads=['sq'], writes=['s4'])
        k.act(s4[:], s4[:], AF.Sqrt, ['s4'], ['s4'])
        k.ts('dve', s4[:], s4[:], 1e-12, None, ALU.max, None, ['s4'], ['s4'])
        k.recip(rn[:], s4[:], ['s4'], ['rn'])
        k.ts('dve', rn[:], rn[:], -1.0, None, ALU.mult, None, ['rn'], ['rn'])
        k.tt('dve', v3(nkk[:]), v3(kkr[:]), bc4(rn[:]), ALU.mult, ['kkr', 'rn'], ['nkk'])
        k.stt(tmp[:], av[:], -1.0, kabc[:], ALU.add, ALU.mult, ['av', VK[3]], ['tmp'])
        k.stt(kmod[:], tmp[:], 1.0, k_, ALU.add, ALU.mult, ['tmp', 'pm'], ['kmod'])
        k.stt(kka[:], nkk[:], -1.0, av[:], ALU.mult, ALU.mult, ['nkk', 'av'], ['kka'])
        k.tt('pool', tmp[:], r_, kmod[:], ALU.mult, ['pm', 'kmod', 'tmp'], ['tmp'])
        k.tt('pool', tmp[:], tmp[:], rkbc[:], ALU.mult, ['tmp', VK[4]], ['tmp'])
        k.P.op('dve', lambda e: e.tensor_reduce(out=bon[:], in_=v3(tmp[:]), axis=AX.X, op=ALU.add), reads=['tmp'], writes=['bon'])
        k.mm(B[3][:, 0:W], triw[:, 0, :], sw[:], True, True, ['triw', 'sw'], [bk(3)])
        k.mm(B[4][:, 0:W], triw[:, 1, :], sw[:], True, True, ['triw', 'sw'], [bk(4)])
        k.mm(B[5][:, 0:W], triw[:, 2, :], sw[:], True, True, ['triw', 'sw'], [bk(5)])
        for h in range(NH):
            k.mm(B[6 + h // 4][0:64, (h % 4) * 128:(h % 4 + 1) * 128], sw[:, h * 64:(h + 1) * 64], triw[:, 0, :], True, True,
                 ['sw', 'triw'], [bk(6 + h // 4)])
        k.act(E1[:], B[3][:, 0:W], AF.Exp, [bk(3)], ['E1'])
        k.act(E2[:], B[3][:, 0:W], AF.Exp, [bk(3)], ['E2'], scale=-1.0)
        k.act(E3[:], B[4][:, 0:W], AF.Exp, [bk(4)], ['E3'])
        k.act(E4[:], B[5][:, 0:W], AF.Exp, [bk(5)], ['E4'])
        for g in range(NG):
            k.act(E1T[:, 4 * g:4 * g + 4, :].rearrange("p a t -> p (a t)"), B[6 + g][0:64, :], AF.Exp, [bk(6 + g)], ['E1T'])
        k.tt('dve', At[:], nkk[:], E3[:], ALU.mult, ['nkk', 'E3'], ['At'])
        k.tt('pool', Bs[:], kka[:], E2[:], ALU.mult, ['kka', 'E2'], ['Bs'])
        k.tt('dve', Ks[:], kmod[:], E2[:], ALU.mult, ['kmod', 'E2'], ['Ks'])
        k.tt('pool', Rt[:], r_, E1[:], ALU.mult, ['pm', 'E1'], ['Rt'])
        for c in range(NCK):
            k.stt(Bfm[c][:], kka[:], rowm[:, c:c + 1], E4[:], ALU.mult, ALU.mult, ['kka', 'E4', 'rowm'], [f'Bfm{c}'])
            k.stt(Kfm[c][:], kmod[:], rowm[:, c:c + 1], E4[:], ALU.mult, ALU.mult, ['kmod', 'E4', 'rowm'], [f'Kfm{c}'])
        HS = list(range(NH))
        for h in HS:
            cs_ = slice(h * 64, (h + 1) * 64)
            for q, (src, key) in enumerate([(At, 'At'), (Bs, 'Bs'), (Ks, 'Ks'), (Rt, 'Rt')]):
                k.tr(B[h][0:64, q * 128:(q + 1) * 128], src[:, cs_], k.identf[:], [key], [bk(h)])
        for h in HS:
            k.cp('act' if h % 2 else 'dve', FT[h][:].rearrange("p a t -> p (a t)"), B[h][0:64, :], [bk(h)], [f'FT{h}'])
        for h in HS:
            AtT, BsT, KsT, RtT = (FT[h][:, q, :] for q in range(4))
            o = lambda j: B[h][:, j * 128:(j + 1) * 128]
            k.mm(o(0), BsT, AtT, True, True, [f'FT{h}'], [bk(h)])
            k.mm(o(1), AtT, BsT, True, True, [f'FT{h}'], [bk(h)])
            k.mm(o(2), KsT, AtT, True, True, [f'FT{h}'], [bk(h)])
        for h in HS:
            k.tt('dve', A5[h][:, 0:384], B[h][:, 0:384], mask5[:, 0:384], ALU.mult, [bk(h), 'mask5'], [f'A5_{h}'])
        for h in HS:
            AtT, BsT, KsT, RtT = (FT[h][:, q, :] for q in range(4))
            k.mm(B[h][:, 0:128], BsT, RtT, True, True, [f'FT{h}'], [bk(h)])
            k.mm(B[h][:, 128:256], KsT, RtT, True, True, [f'FT{h}'], [bk(h)])
        for h in HS:
            k.tt('dve', A5[h][:, 384:640], B[h][:, 0:256], mask5[:, 384:640], ALU.mult, [bk(h), 'mask5'], [f'A5b_{h}'])
            k.cp('act', NL[h][:], rd(A5[h][:, 0:256]), [f'A5_{h}'], [f'NL_{h}'])
            k.tt('pool' if not fr else 'dve', PQ[h][:].rearrange("p (a n) -> p a n", a=2), rd(A5[h][:, 0:256]).rearrange("p (a n) -> p a n", a=2),
                 k.identf[:].unsqueeze(1).broadcast_to([128, 2, 128]), ALU.add, [f'A5_{h}', 'ident'], [f'PQ_{h}'])
        for lev in range(nlev):
            for h in HS:
                N_, L_ = NL[h][:, 0:128], NL[h][:, 128:256]
                k.mm(B[h][:, 0:128], L_, N_, True, True, [f'NL_{h}'], [bk(h)])
                k.mm(B[h][:, 128:256], N_, L_, True, True, [f'NL_{h}'], [bk(h)])
            for h in HS:
                k.cp('act', NL[h][:], B[h][:, 0:256], [bk(h)], [f'NL_{h}'])
            for h in HS:
                N_, L_ = NL[h][:, 0:128], NL[h][:, 128:256]
                P_, Q_ = PQ[h][:, 0:128], PQ[h][:, 128:256]
                k.mm(B[h][:, 256:384], Q_, N_, True, True, [f'NL_{h}', f'PQ_{h}'], [bk(h)])
                k.mm(B[h][:, 384:512], P_, L_, True, True, [f'NL_{h}', f'PQ_{h}'], [bk(h)])
            for h in HS:
                k.tt('dve', PQ[h][:], B[h][:, 256:512], rd(PQ[h][:]), ALU.add, [bk(h), f'PQ_{h}'], [f'PQ_{h}'])
        for h in range(NH):
            k.mm(B[0][:, h * 64:(h + 1) * 64], A5[h][:, 256:384], vr[:, h * 64:(h + 1) * 64], True, True, [f'A5_{h}', 'vr'], [bk(0)])
        k.cp('act', W1[:], B[0][:, 0:W], [bk(0)], ['W1'])
        for h in range(NH):
            k.mm(B[1][:, h * 64:(h + 1) * 64], PQ[h][:, 0:128], W1[:, h * 64:(h + 1) * 64], True, True,
                 [f'PQ_{h}', 'W1'], [bk(1)])
        k.cp('act', U1[:], B[1][:, 0:W], [bk(1)], ['U1'])
        vsrc = vr if frc else None
        for c in range(NCK):
            cr = slice(c * CH, (c + 1) * CH)
            for h in range(NH):
                k.mm(B[2][cr, h * 64:(h + 1) * 64], lhc(FT[h][:, 0, cr]), ST[h][:], True, True, [f'FT{h}', f'ST{h}'], [bk(2)])
            k.cp('act', P1s[cr, :], B[2][cr, 0:W], [bk(2)], ['P1s'])
            for h in range(NH):
                k.mm(B[3][cr, h * 64:(h + 1) * 64], lhc(PQ[h][:, cr]), P1s[:, h * 64:(h + 1) * 64], True, True,
                     [f'PQ_{h}', 'P1s'], [bk(3)])
            k.tt('dve', Us[cr, :], B[3][cr, 0:W], U1[cr, :], ALU.add, [bk(3), 'U1'], ['Us'])
            for h in range(NH):
                hc_ = slice(h * 64, (h + 1) * 64)
                vh = vr[:, hc_] if frc else pm[:, 2 * W + h * 64:2 * W + (h + 1) * 64]
                vk = 'vr' if frc else 'pm'
                k.mm(B[6][cr, hc_], lhc(FT[h][:, 3, cr]), ST[h][:], True, False, [f'FT{h}', f'ST{h}'], [bk(6)])
                k.mm(B[6][cr, hc_], lhc(A5[h][:, 384:512][:, cr]), Us[:, hc_], False, False, [f'A5b_{h}', 'Us'], [bk(6)])
                k.mm(B[6][cr, hc_], lhc(A5[h][:, 512:640][:, cr]), vh, False, True, [f'A5b_{h}', vk], [bk(6)])
            for h in range(NH):
                hc_ = slice(h * 64, (h + 1) * 64)
                vh = pm[:, 2 * W + h * 64:2 * W + (h + 1) * 64]
                k.mm(B[7][0:64, hc_], Bfm[c][:, hc_], rdc(Us[:, hc_]), True, False, [f'Bfm{c}', 'Us'], [bk(7)])
                k.mm(B[7][0:64, hc_], Kfm[c][:, hc_], vh, False, True, [f'Kfm{c}', 'pm'], [bk(7)])
            for h in range(NH):
                hc_ = slice(h * 64, (h + 1) * 64)
                k.stt(ST[h][:], rdc(ST[h][:]), E1T[:, h, (c + 1) * CH - 1:(c + 1) * CH], B[7][0:64, hc_], ALU.mult, ALU.add,
                      [f'ST{h}', 'E1T', bk(7)], [f'ST{h}'])
        k.cp('act', ysb[:], B[6][:, 0:W], [bk(6)], ['ysb'])
        k.P.op('dve', lambda e: e.tensor_reduce(out=m4[:], in_=v3(ysb[:]), axis=AX.X, op=ALU.add), reads=['ysb'], writes=['m4'])
        k.ts('dve', m4[:], m4[:], -1.0 / 64.0, None, ALU.mult, None, ['m4'], ['m4'])
        k.tt('dve', v3(yc[:]), v3(ysb[:]), bc4(m4[:]), ALU.add, ['ysb', 'm4'], ['yc'])
        k.tt('pool', sq[:], yc[:], yc[:], ALU.mult, ['yc'], ['sq'])
        k.P.op('dve', lambda e: e.tensor_reduce(out=r4[:], in_=v3(sq[:]), axis=AX.X, op=ALU.add), reads=['sq'], writes=['r4'])
        k.ts('dve', r4[:], r4[:], 1.0 / 64.0, GN_EPS, ALU.mult, ALU.add, ['r4'], ['r4'])
        k.act(r4[:], r4[:], AF.Sqrt, ['r4'], ['r4'])
        k.recip(r4[:], r4[:], ['r4'], ['r4'])
        k.tt('dve', v3(yc[:]), v3(yc[:]), bc4(r4[:]), ALU.mult, ['yc', 'r4'], ['yc'])
        k.tt('pool', yc[:], yc[:], lngbc[:], ALU.mult, ['yc', VK[5]], ['yc'])
        k.tt('pool', yc[:], yc[:], lnbbc[:], ALU.add, ['yc', VK[6]], ['yc'])
        k.tt('dve', v3(tmp[:]), v3(v_), bc4(bon[:]), ALU.mult, ['pm', 'bon', 'tmp'], ['tmp'])
        k.tt('pool', yc[:], yc[:], tmp[:], ALU.add, ['yc', 'tmp'], ['yc'])
        k.tt('dve', ot[b][:], yc[:], gv[:], ALU.mult, ['yc', 'gv'], [f'ot{b}'])
        k.dma('pool', oc[rows, :], ot[b][:], r=[f'ot{b}'], final=True)
    return k.finish()


def build_RWKVP(L, k=None, CH=64):
    NH, fr = 8, True
    k = k or K()
    NT = L // 128
    W = NH * 64
    NG = NH // 4
    FR = mybir.dt.float32r if fr else F32
    rd = (lambda ap: ap.bitcast(F32)) if fr else (lambda ap: ap)
    NCK = 128 // CH
    nlev = 5 if CH == 64 else 6
    frc = True
    FRC = mybir.dt.float32r if frc else F32
    rdc = (lambda ap: ap.bitcast(F32)) if frc else (lambda ap: ap)
    lhc = (lambda ap: ap) if frc else rd
    prkv = [k.din(nm, [L, W]) for nm in ("pr", "pk", "pv")]
    mu1 = k.din("mu1", [3 * W])
    pls = [k.din("plw", [64, L]), k.din("pla", [64, L]), k.din("plg", [128, L])]
    mul = k.din("mul", [128, 3])
    w2 = k.din("w2", [64, W])
    a2 = k.din("a2", [64, W])
    g2 = k.din("g2", [128, W])
    vecs = k.din("vecs", [7, W])
    ident_d = k.din("ident", [128, 128])
    triw_d = k.din("triw", [3, 128, 128])
    mask5_d = k.din("mask5", [128, 640])
    rowm_d = k.din("rowm", [128, 2])
    oc = k.dout("oc", [L, W])

    k.consts(ident_d)
    triw = k.sb("triw_s", [128, 3, 128])
    k.dma('sp', triw[:], triw_d.rearrange("a p n -> p a n"), w=['triw'])
    mask5 = k.sb("mask5_s", [128, 640])
    k.dma('sp', mask5[:], mask5_d, w=['mask5'])
    rowm = k.sb("rowm_s", [128, 2])
    k.dma('sp', rowm[:], rowm_d, w=['rowm'])
    mu1bc = k.bcast_row("mu1bc", mu1, 3 * W)
    vb = [k.bcast_row(f"vb{i}", vecs[i], W) for i in range(7)]
    w0bc, a0bc, kkbc, kabc, rkbc, lngbc, lnbbc = vb
    VK = [f"vb{i}" for i in range(7)]
    muls = k.sb("muls", [128, 3])
    k.dma('sp', muls[:], mul, w=['muls'])
    w2s = k.sb("w2s", [64, W])
    a2s = k.sb("a2s", [64, W])
    k.dma('sp', w2s[:], w2, w=['w2s'])
    k.dma('sp', a2s[:], a2, w=['a2s'])
    g2s = k.sb("g2s", [128, W])
    k.dma('sp', g2s[:], g2, w=['g2s'])
    ST = [k.sb(f"ST{i}", [64, 64], FRC) for i in range(NH)]
    zt = k.sb("zt", [128, W])
    k.memset('dve', zt[:], 0.0, ['zt'])
    for i in range(NH):
        k.cp('dve', ST[i][:], zt[0:64, 0:64], ['zt'], [f'ST{i}'])
    P1s = k.sb("P1s", [128, W], FRC)
    Us = k.sb("Us", [128, W], FRC)
    k.cp('dve', P1s[:], zt[:], ['zt'], ['P1s'])
    k.cp('dve', Us[:], zt[:], ['zt'], ['Us'])

    pt = [k.sb("pt0", [128, 3 * W])] * 2
    pp = [k.sb("pp0", [128, 3 * W])] * 2
    lt = [k.sb("lt0", [128, 3, 128])] * 2
    lp = [k.sb("lp0", [128, 3, 128])] * 2
    k.memset('pool', lt[0][:], 0.0, ['lt0', 'lt1', 'lt2'])
    k.memset('pool', lp[0][:], 0.0, ['lp0', 'lp1', 'lp2', 'lpz'])
    pm2 = [k.sb(f"pm{i_}", [128, 3 * W]) for i_ in range(2)]
    vr2 = [k.sb(f"vr{i_}", [128, W], FR) for i_ in range(2)]
    lm2 = [k.sb(f"lm{i_}", [128, 3, 128]) for i_ in range(2)]
    sw = k.sb("sw", [128, W])
    av = k.sb("av", [128, W])
    gv2 = [k.sb(f"gv{i_}", [128, W]) for i_ in range(2)]
    kkr = k.sb("kkr", [128, W])
    sq = k.sb("sq", [128, W])
    s4 = k.sb("s4", [128, NH])
    rn = k.sb("rn", [128, NH])
    nkk = k.sb("nkk", [128, W])
    kmod = k.sb("kmod", [128, W])
    kka = k.sb("kka", [128, W])
    tmp = k.sb("tmp", [128, W])
    bon2 = [k.sb(f"bon{i_}", [128, NH]) for i_ in range(2)]
    E1 = k.sb("E1", [128, W])
    E2 = k.sb("E2", [128, W])
    E3 = k.sb("E3", [128, W])
    E4 = k.sb("E4", [128, W])
    E1T2 = [k.sb(f"E1T{i_}", [64, NH, 128]) for i_ in range(2)]
    At2 = [k.sb(f"At{i_}", [128, W]) for i_ in range(2)]
    Bs2 = [k.sb(f"Bs{i_}", [128, W]) for i_ in range(2)]
    Ks2 = [k.sb(f"Ks{i_}", [128, W]) for i_ in range(2)]
    Rt2 = [k.sb(f"Rt{i_}", [128, W]) for i_ in range(2)]
    Bfm2 = [[k.sb(f"Bfm{p_}{c}", [128, W]) for c in range(NCK)] for p_ in range(2)]
    Kfm2 = [[k.sb(f"Kfm{p_}{c}", [128, W]) for c in range(NCK)] for p_ in range(2)]
    sqp = k.sb("sqp", [128, W])
    tmpp = k.sb("tmpp", [128, W])
    FT = [k.sb(f"FT{h}", [64, 4, 128], FR) for h in range(NH)]
    A5 = [k.sb(f"A5_{h}", [128, 640], FR) for h in range(NH)]
    NL = [k.sb(f"NL_{h}", [128, 256], FR) for h in range(NH)]
    PQ = [k.sb(f"PQ_{h}", [128, 128], FR) for h in range(NH)]
    W1 = k.sb("W1", [128, W], FR)
    U1 = k.sb("U1", [128, W])
    ysb = k.sb("ysb", [128, W])
    yc = k.sb("yc", [128, W])
    m4 = k.sb("m4", [128, NH])
    r4 = k.sb("r4", [128, NH])
    ot = [k.sb(f"ot{i}", [128, W]) for i in range(2)]
    B = [k.ps(f"psB{i}", [128, 512]) for i in range(8)]
    bk = lambda i: f'psB{i}'
    v3 = lambda t: t.rearrange("p (h j) -> p h j", h=NH)
    bc4 = lambda t: t.unsqueeze(2).broadcast_to([128, NH, 64])


    S0, S1, C0, C1 = 6, 7, 4, 5

    def tile(i):
        b = i % 2
        pm, lm = pm2[b], lm2[b]
        kpm, klm = f'pm{b}', f'lm{b}'
        At, Bs, Ks, Rt, gv, vr, bon, E1T, Bf, Kf = At2[b], Bs2[b], Ks2[b], Rt2[b], gv2[b], vr2[b], bon2[b], E1T2[b], Bfm2[b], Kfm2[b]
        kAt, kBs, kKs, kRt, kgv, kvr, kbon, kE1T, kBf, kKf = (f'{n_}{b}' for n_ in ('At', 'Bs', 'Ks', 'Rt', 'gv', 'vr', 'bon', 'E1T', 'Bf', 'Kf'))
        rows = slice(i * 128, (i + 1) * 128)
        PK, PPK, LTK, LPK = [], [], [], []
        for q in range(3):
            cq = slice(q * W, (q + 1) * W)
            k.dma('sp', pt[b][:, cq], prkv[q][rows, :], w=[f'pt{q}'])
            PK.append(f'pt{q}')
            if i == 0:
                k.dma('sp', pp[b][1:128, cq], prkv[q][0:127, :], w=[f'pp{q}'])
            else:
                k.dma('sp', pp[b][:, cq], prkv[q][i * 128 - 1:i * 128 + 127, :], w=[f'pp{q}'])
            PPK.append(f'pp{q}')
            nr = pls[q].shape[0]
            k.dma('sp', lt[b][0:nr, q, :], pls[q][:, rows], w=[f'lt{q}'])
            LTK.append(f'lt{q}')
            if i == 0:
                k.dma('sp', lp[b][0:nr, q, 1:128], pls[q][:, 0:127], w=[f'lp{q}'])
            else:
                k.dma('sp', lp[b][0:nr, q, :], pls[q][:, i * 128 - 1:i * 128 + 127], w=[f'lp{q}'])
            LPK.append(f'lp{q}')
        if i == 0:
            k.memset('pool', pp[b][0:1, :], 0.0, ['ppz'])
            k.memset('pool', lp[b][:, :, 0:1], 0.0, ['lpz'])
            PPK.append('ppz')
            LPK.append('lpz')
        k.tt('dve', pm[:], pp[b][:], pt[b][:], ALU.subtract, PPK + PK, [kpm])
        k.tt('dve', pm[:], pm[:], mu1bc[:], ALU.mult, [kpm, 'mu1bc'], [kpm])
        k.tt('dve', pm[:], pm[:], pt[b][:], ALU.add, [kpm] + PK, [kpm])
        r_, k_, v_ = pm[:, 0:W], pm[:, W:2 * W], pm[:, 2 * W:3 * W]
        LK = LTK + LPK
        k.tt('dve', lm[:], lp[b][:], lt[b][:], ALU.subtract, LK, [klm])
        for blk in range(3):
            k.stt(lm[:, blk, :], lm[:, blk, :], muls[:, blk:blk + 1], lt[b][:, blk, :], ALU.mult, ALU.add,
                  [klm, 'muls'] + LK, [klm])
        k.act(lm[0:64, 0, :], lm[0:64, 0, :], AF.Tanh, [klm], [klm])
        k.act(lm[:, 2, :], lm[:, 2, :], AF.Sigmoid, [klm], [klm])
        yield
        k.cp('act', vr[:], v_, [kpm], [kvr])
        k.mm(B[S0][:, 0:W], lm[0:64, 0, :], w2s[:], True, True, [klm, 'w2s'], [bk(S0)])
        k.mm(B[S1][:, 0:W], lm[0:64, 1, :], a2s[:], True, True, [klm, 'a2s'], [bk(S1)])
        k.tt('dve', sw[:], B[S0][:, 0:W], w0bc[:], ALU.add, [bk(S0), VK[0]], ['sw'])
        k.act(sw[:], sw[:], AF.Sigmoid, ['sw'], ['sw'])
        k.tt('dve', av[:], B[S1][:, 0:W], a0bc[:], ALU.add, [bk(S1), VK[1]], ['av'])
        k.act(av[:], av[:], AF.Sigmoid, ['av'], ['av'])
        k.mm(B[S0][:, 0:W], lm[:, 2, :], g2s[:], True, True, [klm, 'g2s'], [bk(S0)])
        k.cp('act', gv[:], B[S0][:, 0:W], [bk(S0)], [kgv])
        yield
        k.tt('dve', kkr[:], k_, kkbc[:], ALU.mult, [kpm, VK[2]], ['kkr'])
        k.tt('dve', sq[:], kkr[:], kkr[:], ALU.mult, ['kkr'], ['sq'])
        k.P.op('dve', lambda e: e.tensor_reduce(out=s4[:], in_=v3(sq[:]), axis=AX.X, op=ALU.add), reads=['sq'], writes=['s4'])
        k.act(s4[:], s4[:], AF.Sqrt, ['s4'], ['s4'])
        k.ts('dve', s4[:], s4[:], 1e-12, None, ALU.max, None, ['s4'], ['s4'])
        k.recip(rn[:], s4[:], ['s4'], ['rn'])
        k.ts('dve', rn[:], rn[:], -1.0, None, ALU.mult, None, ['rn'], ['rn'])
        k.tt('dve', v3(nkk[:]), v3(kkr[:]), bc4(rn[:]), ALU.mult, ['kkr', 'rn'], ['nkk'])
        k.stt(tmp[:], av[:], -1.0, kabc[:], ALU.add, ALU.mult, ['av', VK[3]], ['tmp'])
        k.stt(kmod[:], tmp[:], 1.0, k_, ALU.add, ALU.mult, ['tmp', kpm], ['kmod'])
        k.stt(kka[:], nkk[:], -1.0, av[:], ALU.mult, ALU.mult, ['nkk', 'av'], ['kka'])
        k.tt('dve', tmp[:], r_, kmod[:], ALU.mult, [kpm, 'kmod', 'tmp'], ['tmp'])
        k.tt('dve', tmp[:], tmp[:], rkbc[:], ALU.mult, ['tmp', VK[4]], ['tmp'])
        k.P.op('dve', lambda e: e.tensor_reduce(out=bon[:], in_=v3(tmp[:]), axis=AX.X, op=ALU.add), reads=['tmp'], writes=[kbon])
        k.mm(B[S1][:, 0:W], triw[:, 0, :], sw[:], True, True, ['triw', 'sw'], [bk(S1)])
        k.mm(B[S0][:, 0:W], triw[:, 1, :], sw[:], True, True, ['triw', 'sw'], [bk(S0)])
        k.act(E1[:], B[S1][:, 0:W], AF.Exp, [bk(S1)], ['E1'])
        k.act(E2[:], B[S1][:, 0:W], AF.Exp, [bk(S1)], ['E2'], scale=-1.0)
        k.act(E3[:], B[S0][:, 0:W], AF.Exp, [bk(S0)], ['E3'])
        k.mm(B[S1][:, 0:W], triw[:, 2, :], sw[:], True, True, ['triw', 'sw'], [bk(S1)])
        k.act(E4[:], B[S1][:, 0:W], AF.Exp, [bk(S1)], ['E4'])
        for g in range(2):
            for hl in range(4):
                h = 4 * g + hl
                k.mm(B[S0 + g][0:64, hl * 128:(hl + 1) * 128], sw[:, h * 64:(h + 1) * 64], triw[:, 0, :], True, True,
                     ['sw', 'triw'], [bk(S0 + g)])
        for g in range(2):
            k.act(E1T[:, 4 * g:4 * g + 4, :].rearrange("p a t -> p (a t)"), B[S0 + g][0:64, :], AF.Exp, [bk(S0 + g)], [kE1T])
        yield
        k.tt('dve', At[:], nkk[:], E3[:], ALU.mult, ['nkk', 'E3'], [kAt])
        k.tt('dve', Bs[:], kka[:], E2[:], ALU.mult, ['kka', 'E2'], [kBs])
        k.tt('dve', Ks[:], kmod[:], E2[:], ALU.mult, ['kmod', 'E2'], [kKs])
        k.tt('dve', Rt[:], r_, E1[:], ALU.mult, [kpm, 'E1'], [kRt])
        for c in range(NCK):
            k.stt(Bf[c][:], kka[:], rowm[:, c:c + 1], E4[:], ALU.mult, ALU.mult, ['kka', 'E4', 'rowm'], [kBf])
            k.stt(Kf[c][:], kmod[:], rowm[:, c:c + 1], E4[:], ALU.mult, ALU.mult, ['kmod', 'E4', 'rowm'], [kKf])
        yield
        for g in range(2):
            HS = list(range(4 * g, 4 * g + 4))
            for h in HS:
                hl = h % 4
                cs_ = slice(h * 64, (h + 1) * 64)
                for q, (src, key) in enumerate([(At, kAt), (Bs, kBs), (Ks, kKs), (Rt, kRt)]):
                    k.tr(B[hl][0:64, q * 128:(q + 1) * 128], src[:, cs_], k.identf[:], [key], [bk(hl)])
            for h in HS:
                hl = h % 4
                k.cp('act' if h % 2 else 'dve', FT[h][:].rearrange("p a t -> p (a t)"), B[hl][0:64, :], [bk(hl)], [f'FT{h}'])
            for h in HS:
                hl = h % 4
                AtT, BsT, KsT, RtT = (FT[h][:, q, :] for q in range(4))
                k.mm(B[hl][:, 0:128], BsT, AtT, True, True, [f'FT{h}'], [bk(hl)])
                k.mm(B[hl][:, 128:256], AtT, BsT, True, True, [f'FT{h}'], [bk(hl)])
                k.mm(B[hl][:, 256:384], KsT, AtT, True, True, [f'FT{h}'], [bk(hl)])
            for h in HS:
                hl = h % 4
                k.tt('dve', A5[h][:, 0:384], B[hl][:, 0:384], mask5[:, 0:384], ALU.mult, [bk(hl), 'mask5'], [f'A5_{h}'])
            for h in HS:
                hl = h % 4
                AtT, BsT, KsT, RtT = (FT[h][:, q, :] for q in range(4))
                k.mm(B[hl][:, 0:128], BsT, RtT, True, True, [f'FT{h}'], [bk(hl)])
                k.mm(B[hl][:, 128:256], KsT, RtT, True, True, [f'FT{h}'], [bk(hl)])
            for h in HS:
                hl = h % 4
                k.tt('dve', A5[h][:, 384:640], B[hl][:, 0:256], mask5[:, 384:640], ALU.mult, [bk(hl), 'mask5'], [f'A5b_{h}'])
                k.cp('act', NL[h][:], rd(A5[h][:, 0:256]), [f'A5_{h}'], [f'NL_{h}'])
                k.tt('dve', PQ[h][:, 0:128], rd(A5[h][:, 0:128]), k.identf[:], ALU.add, [f'A5_{h}', 'ident'], [f'PQ_{h}'])
            for lev in range(nlev):
                last = (lev == nlev - 1)
                for h in HS:
                    hl = h % 4
                    N_, L_ = NL[h][:, 0:128], NL[h][:, 128:256]
                    k.mm(B[hl][:, 0:128], L_, N_, True, True, [f'NL_{h}'], [bk(hl)])
                    k.mm(B[hl][:, 128:256], N_, L_, True, True, [f'NL_{h}'], [bk(hl)])
                for h in HS:
                    hl = h % 4
                    k.cp('act', NL[h][:], B[hl][:, 0:256], [bk(hl)], [f'NL_{h}'])
                for h in HS:
                    hl = h % 4
                    k.mm(B[hl][:, 256:384], NL[h][:, 128:256], PQ[h][:, 0:128], True, True, [f'NL_{h}', f'PQ_{h}'], [bk(hl)])
                for h in HS:
                    hl = h % 4
                    k.tt('dve', PQ[h][:, 0:128], B[hl][:, 256:384], rd(PQ[h][:, 0:128]), ALU.add, [bk(hl), f'PQ_{h}'], [f'PQ_{h}'])
            yield
        for h in range(NH):
            k.mm(B[C0][:, h * 64:(h + 1) * 64], A5[h][:, 256:384], vr[:, h * 64:(h + 1) * 64], True, True, [f'A5_{h}', kvr], [bk(C0)])
        k.cp('act', W1[:], B[C0][:, 0:W], [bk(C0)], ['W1'])
        for h in range(NH):
            k.mm(B[C1][:, h * 64:(h + 1) * 64], PQ[h][:, 0:128], W1[:, h * 64:(h + 1) * 64], True, True,
                 [f'PQ_{h}', 'W1'], [bk(C1)])
        k.cp('act', U1[:], B[C1][:, 0:W], [bk(C1)], ['U1'])
        for c in range(NCK):
            cr = slice(c * CH, (c + 1) * CH)
            for h in range(NH):
                k.mm(B[C0][:, h * 64:(h + 1) * 64], FT[h][:, 0, :], ST[h][:], True, True, [f'FT{h}', f'ST{h}'], [bk(C0)])
            k.cp('act', P1s[cr, :], B[C0][cr, 0:W], [bk(C0)], ['P1s'])
            for h in range(NH):
                k.mm(B[C0][:, h * 64:(h + 1) * 64], PQ[h][:, :], P1s[:, h * 64:(h + 1) * 64], True, True,
                     [f'PQ_{h}', 'P1s'], [bk(C0)])
            k.tt('dve', Us[cr, :], B[C0][cr, 0:W], U1[cr, :], ALU.add, [bk(C0), 'U1'], ['Us'])
            for h in range(NH):
                hc_ = slice(h * 64, (h + 1) * 64)
                k.mm(B[C0][:, hc_], FT[h][:, 3, :], ST[h][:], True, False, [f'FT{h}', f'ST{h}'], [bk(C0)])
                k.mm(B[C0][:, hc_], A5[h][:, 384:512], Us[:, hc_], False, False, [f'A5b_{h}', 'Us'], [bk(C0)])
                k.mm(B[C0][:, hc_], A5[h][:, 512:640], vr[:, hc_], False, True, [f'A5b_{h}', kvr], [bk(C0)])
            k.cp('act', ysb[cr, :], B[C0][cr, 0:W], [bk(C0)], ['ysb'])
            for h in range(NH):
                hc_ = slice(h * 64, (h + 1) * 64)
                k.mm(B[C1][0:64, hc_], Bf[c][:, hc_], rdc(Us[:, hc_]), True, False, [kBf, 'Us'], [bk(C1)])
                k.mm(B[C1][0:64, hc_], Kf[c][:, hc_], rd(vr[:, hc_]), False, True, [kKf, kvr], [bk(C1)])
            for h in range(NH):
                hc_ = slice(h * 64, (h + 1) * 64)
                k.stt(ST[h][:], rdc(ST[h][:]), E1T[:, h, (c + 1) * CH - 1:(c + 1) * CH], B[C1][0:64, hc_], ALU.mult, ALU.add,
                      [f'ST{h}', kE1T, bk(C1)], [f'ST{h}'])
        k.P.op('dve', lambda e: e.tensor_reduce(out=m4[:], in_=v3(ysb[:]), axis=AX.X, op=ALU.add), reads=['ysb'], writes=['m4'])
        k.ts('dve', m4[:], m4[:], -1.0 / 64.0, None, ALU.mult, None, ['m4'], ['m4'])
        k.tt('dve', v3(yc[:]), v3(ysb[:]), bc4(m4[:]), ALU.add, ['ysb', 'm4'], ['yc'])
        k.tt('dve', sqp[:], yc[:], yc[:], ALU.mult, ['yc'], ['sqp'])
        k.P.op('dve', lambda e: e.tensor_reduce(out=r4[:], in_=v3(sqp[:]), axis=AX.X, op=ALU.add), reads=['sqp'], writes=['r4'])
        k.ts('dve', r4[:], r4[:], 1.0 / 64.0, GN_EPS, ALU.mult, ALU.add, ['r4'], ['r4'])
        k.act(r4[:], r4[:], AF.Sqrt, ['r4'], ['r4'])
        k.recip(r4[:], r4[:], ['r4'], ['r4'])
        k.tt('dve', v3(yc[:]), v3(yc[:]), bc4(r4[:]), ALU.mult, ['yc', 'r4'], ['yc'])
        k.tt('dve', yc[:], yc[:], lngbc[:], ALU.mult, ['yc', VK[5]], ['yc'])
        k.tt('dve', yc[:], yc[:], lnbbc[:], ALU.add, ['yc', VK[6]], ['yc'])
        k.tt('dve', v3(tmpp[:]), v3(rd(vr[:])), bc4(bon[:]), ALU.mult, [kvr, kbon], ['tmpp'])
        k.tt('dve', yc[:], yc[:], tmpp[:], ALU.add, ['yc', 'tmpp'], ['yc'])
        k.tt('dve', ot[b][:], yc[:], gv[:], ALU.mult, ['yc', kgv], [f'ot{b}'])
        k.dma('pool', oc[rows, :], ot[b][:], r=[f'ot{b}'], final=True)

    gens = {}

    def adv(j):
        if 0 <= j < NT:
            try:
                next(gens[j])
            except StopIteration:
                pass

    for step in range(NT + 2):
        if step < NT:
            gens[step] = tile(step)
            adv(step)
        for r_i in range(3):
            adv(step - 1)
            adv(step - 2)
    return k.finish()


def rwkv_consts(CH=64):
    c = -math.exp(-0.5)
    blk = np.kron(np.eye(128 // CH), np.ones((CH, CH)))
    s_idx = np.arange(128)[:, None]
    t_idx = np.arange(128)[None, :]
    triw = np.stack([c * blk * (s_idx <= t_idx), c * blk * (s_idx < t_idx), c * blk * (s_idx > t_idx)]).astype(np.float32)
    lt_, le_, gt_ = blk * (s_idx < t_idx), blk * (s_idx <= t_idx), blk * (t_idx < s_idx)
    mask5 = np.concatenate([lt_, gt_, lt_, le_, le_], 1).astype(np.float32)
    rowm = np.stack([(np.arange(128) < 64), (np.arange(128) >= 64)], 1).astype(np.float32) if CH == 64 else np.ones((128, 2), np.float32)
    return dict(ident=np.eye(128, dtype=np.float32), triw=triw, mask5=mask5, rowm=rowm)


def rwkv_host_inputs(s, p_rwkv, prm, NH=4, CH=64):
    L = p_rwkv.shape[0]
    cs = slice(64 * NH * s, 64 * NH * (s + 1))
    r_, w1, k_, v_, a1, g1 = np.split(p_rwkv, np.cumsum([512, 64, 512, 512, 64])[:5], axis=-1)
    mu = prm['rwkv_mu']
    mur, muw1, muk, muv, mua1, mug1 = np.split(mu, np.cumsum([512, 64, 512, 512, 64])[:5])
    zm = np.zeros(64, np.float32)
    mul = np.concatenate([muw1, zm, mua1, zm, mug1]).reshape(3, 128).T
    vecs = np.stack([prm['rwkv_w0'][cs], prm['rwkv_a0'][cs], prm['rwkv_k_k'][cs], prm['rwkv_k_a'][cs],
                     prm['rwkv_r_k'].reshape(-1)[cs], prm['rwkv_ln_gain'][cs], prm['rwkv_ln_bias'][cs]])
    c_ = np.ascontiguousarray
    d = dict(pr=c_(r_[:, cs]), pk=c_(k_[:, cs]), pv=c_(v_[:, cs]),
             mu1=c_(np.concatenate([mur[cs], muk[cs], muv[cs]])),
             plw=c_(w1.T), pla=c_(a1.T), plg=c_(g1.T), mul=c_(mul),
             w2=c_(prm['rwkv_w2'][:, cs]), a2=c_(prm['rwkv_a2'][:, cs]),
             g2=c_(prm['rwkv_g2'][:, cs]), vecs=c_(vecs))
    d.update(rwkv_consts(CH))
    return d


FM0 = [(0, 128, 0), (128, 128, 128), (256, 128, 256), (384, 128, 384), (1536, 16, 512)] + \
      [(1552 + j * 128, 128, 528 + j * 128) for j in range(4)]
NF0 = 1040
FM1 = [(512, 64, 0), (1600, 64, 64), (1664, 128, 128)] + [(1792 + j * 128, 128, 256 + j * 128) for j in range(8)]
NF1 = 1280


def host_params(inp):
    c_ = lambda a: np.ascontiguousarray(np.asarray(a), dtype=np.float32)
    P = {}
    P['ident'] = np.eye(128, dtype=np.float32)
    P['triu'] = np.triu(np.ones((128, 128), np.float32))
    P['trigt'] = np.tril(np.ones((128, 128), np.float32), -1)
    for l in range(2):
        for j in range(7):
            P[f'g{l}_{j}'] = c_(inp['norm_gain'][l][j])
        for nm in ('xa_wq', 'xa_wk', 'xa_wv', 'xa_wo', 'mlp_w1', 'mlp_w2'):
            P[f'{nm}{l}'] = c_(inp[nm][l])
    P['w_in0'] = c_(inp['ab_w_in'][0])
    P['w_in1'] = c_(inp['cd_w_in'][0])
    P['w_out0'] = c_(inp['ab_w_out'][0])
    P['w_out1'] = c_(inp['cd_w_out'][0])
    P['wglu'] = c_(inp['s5_w_glu'][0])
    P['bglu'] = c_(inp['s5_b_glu'][0])
    prm0 = {k_: np.asarray(inp[k_][0]) for k_ in inp if k_.startswith('s5_') or k_.startswith('gla_')}
    prm1 = {k_: np.asarray(inp[k_][0]) for k_ in inp if k_.startswith('rwkv_') or k_.startswith('lru_')}
    for s in range(2):
        cs = slice(s * 128, (s + 1) * 128)
        P[f'gla_w2_{s}'] = c_(prm0['gla_w_decay2'][:, cs])
        P[f'gla_bd_{s}'] = c_(prm0['gla_b_decay'][None, cs])
        P[f'gla_gn_{s}'] = c_(prm0['gla_norm_gain'][2 * s:2 * s + 2].reshape(256))
        d = s5_host_inputs(s, np.zeros((2, 512), np.float32), prm0)
        for nm in ('lam_re', 'lam_im', 'lstep', 'Bre', 'Bim', 'Cre', 'Cim', 'dsk'):
            P[f's5_{nm}_{s}'] = c_(d[nm])
        P['iota_p'] = c_(d['iota_p'])
        P['iota_f'] = c_(d['iota_f'])
        if s == 0:
            d = rwkv_host_inputs(0, np.zeros((2, 1792), np.float32), prm1, 8, 64)
            for nm in ('mu1', 'mul', 'w2', 'a2', 'g2', 'vecs'):
                P[f'rw_{nm}'] = c_(d[nm])
            for nm in ('triw', 'mask5', 'rowm'):
                P[f'rw_{nm}'] = c_(d[nm])
        d = lru_host_inputs(s, np.zeros((2, 512), np.float32), np.zeros((2, 512), np.float32), prm1)
        for nm in ('cw', 'cb', 'Wa', 'Wx', 'ba', 'bx', 'lam'):
            P[f'lru_{nm}_{s}'] = c_(d[nm])
    return P


def build_fused(P, L):
    k = K(fused=True)
    X = {nm: k.xin(nm, a.shape) for nm, a in P.items()}
    x = k.xin('x', [L, D])
    mem = k.xin('mem', [256, D])
    out = k.xout('out', [L, D])
    proj0 = k.scratch('proj0', [L, 2064])
    PT0 = k.scratch('PT0', [NF0, L])
    proj1 = k.scratch('proj1', [L, 2816])
    PT1 = k.scratch('PT1', [NF1, L])
    o = k.scratch('o', [L, D])
    odT = k.scratch('odT', [512, L])
    h1 = k.scratch('h1', [L, D])
    h2 = k.scratch('h2', [L, D])
    h3 = k.scratch('h3', [L, D])

    def cblock(l, hin, hout, glu, ob_fm):
        io = dict(oa=o[:, 0:512], hin=hin, wout=X[f'w_out{l}'], g1=X[f'g{l}_1'], ident=X['ident'], hout=h1)
        if ob_fm:
            io['obT'] = odT
        else:
            io['ob'] = o[:, 512:1024]
        if glu:
            io.update(wglu=X['wglu'], bglu=X['bglu'])
        k.begin_phase(f'C1_{l}', io)
        build_C1(L, glu, k=k, ob_fm=ob_fm)
        k.begin_phase(f'C2_{l}', dict(hin=h1, mem=mem, wq=X[f'xa_wq{l}'], wk=X[f'xa_wk{l}'], wv=X[f'xa_wv{l}'], wo=X[f'xa_wo{l}'],
                                      g2=X[f'g{l}_2'], g3=X[f'g{l}_3'], g6=X[f'g{l}_6'], ident=X['ident'], hout=h2))
        build_C2(L, k=k)
        k.begin_phase(f'C3_{l}', dict(hin=h2, w1=X[f'mlp_w1{l}'], w2=X[f'mlp_w2{l}'], g4=X[f'g{l}_4'], g5=X[f'g{l}_5'],
                                      ident=X['ident'], hout=hout))
        build_C3(L, k=k)

    k.begin_phase('A0', dict(x=x, gain=X['g0_0'], W=X['w_in0'], ident=X['ident'], out=proj0, outT=PT0))
    build_A2(L, 2064, FM0, NF0, k=k)
    for s in range(2):
        io_g = dict(qT=PT0[s * 128:(s + 1) * 128, :], kT=PT0[256 + s * 128:256 + (s + 1) * 128, :],
                    ktok=proj0[:, 256 + s * 128:256 + (s + 1) * 128], v=proj0[:, 512 + s * 256:512 + (s + 1) * 256],
                    gate=proj0[:, 1024 + s * 256:1024 + (s + 1) * 256], dlrT=PT0[512:528, :],
                    w2=X[f'gla_w2_{s}'], bdec=X[f'gla_bd_{s}'], gn=X[f'gla_gn_{s}'], triu=X['triu'],
                    trigt=X['trigt'], oa=o[:, s * 256:(s + 1) * 256])
        k.begin_phase(f'GLA{s}', io_g)
        build_GLA(L, k=k)
    for s in range(2):
        io_s = dict(uT=PT0[528 + s * 256:528 + (s + 1) * 256, :], u=proj0[:, 1552 + s * 256:1552 + (s + 1) * 256],
                    triu=X['triu'], iota_p=X['iota_p'], iota_f=X['iota_f'], y=o[:, 512 + s * 256:512 + (s + 1) * 256])
        for nm in ('lam_re', 'lam_im', 'lstep', 'Bre', 'Bim', 'Cre', 'Cim', 'dsk'):
            io_s[nm] = X[f's5_{nm}_{s}']
        k.begin_phase(f'S5{s}', io_s)
        build_S5(L, k=k)
    cblock(0, x, h3, True, False)
    k.begin_phase('A1', dict(x=h3, gain=X['g1_0'], W=X['w_in1'], ident=X['ident'], out=proj1, outT=PT1))
    build_A2(L, 2816, FM1, NF1, k=k)
    io = dict(pr=proj1[:, 0:512], pk=proj1[:, 576:1088], pv=proj1[:, 1088:1600], plw=PT1[0:64, :], pla=PT1[64:128, :],
              plg=PT1[128:256, :], ident=X['ident'], triw=X['rw_triw'], mask5=X['rw_mask5'], rowm=X['rw_rowm'], oc=o[:, 0:512])
    for nm in ('mu1', 'mul', 'w2', 'a2', 'g2', 'vecs'):
        io[nm] = X[f'rw_{nm}']
    k.begin_phase('RW', io)
    build_RWKVP(L, k=k, CH=64)
    streams = []
    for s in range(2):
        io = dict(xbT=PT1[256 + s * 256:256 + (s + 1) * 256, :], gateT=PT1[768 + s * 256:768 + (s + 1) * 256, :],
                  odT=odT[s * 256:(s + 1) * 256, :])
        for nm in ('cw', 'cb', 'Wa', 'Wx', 'ba', 'bx', 'lam'):
            io[nm] = X[f'lru_{nm}_{s}']
        streams.append((f'l{s}_', io, lambda kk: gen_LRU(L, kk)))
    k.begin_phase('LRU', {})
    run_streams(k, streams)
    k.finish()
    cblock(1, h3, out, False, True)
    return k.finish_program()


BATCH, SEQ = 4, 4096
_CACHE = {}


def kernel(**inp):
    inp = {k_: np.asarray(v_) for k_, v_ in inp.items()}
    P = host_params(inp)
    if 'nc' not in _CACHE:
        _CACHE['nc'] = build_fused(P, SEQ)
    nc = _CACHE['nc']
    maps = []
    for b in range(BATCH):
        m = dict(P)
        m['x'] = np.ascontiguousarray(inp['x'][b], dtype=np.float32)
        m['mem'] = np.ascontiguousarray(inp['mem'][b], dtype=np.float32)
        maps.append(m)
    res = run_bass_kernel_spmd(nc, maps, core_ids=list(range(BATCH))).results
    return np.ascontiguousarray(np.stack([res[b]['out'] for b in range(BATCH)]).astype(np.float32))
```

```python
import os
import math
from contextlib import ExitStack


import numpy as np
import concourse.bass as bass
import concourse.mybir as mybir
from concourse.bass_utils import run_bass_kernel_spmd

F32 = mybir.dt.float32
BF16 = mybir.dt.bfloat16
I32 = mybir.dt.int32
AF = mybir.ActivationFunctionType
ALU = mybir.AluOpType
AX = mybir.AxisListType

ENGS = ['pe', 'act', 'dve', 'pool', 'sp']
NDMA_SLOTS = 8
SAME_ENGINE_SYNC = os.environ.get("NOSELF", "0") != "1"


class Prog:
    def __init__(self, nc):
        self.nc = nc
        self.ops = {e: [] for e in ENGS}
        self.cnt = {e: 0 for e in ENGS}
        self.last_w = {}
        self.readers = {}
        self.seen = {e: {} for e in ENGS}
        self.dma_n = {e: 0 for e in ENGS}
        self.dma_tok = {e: [None] * NDMA_SLOTS for e in ENGS}
        self.final_tokens = []
        from contextlib import ExitStack
        self.sem_stack = ExitStack()
        self.sems = {}
        for e in ['pe', 'act', 'dve', 'pool']:
            self.sems[('c', e)] = self.sem_stack.enter_context(nc.semaphore("s_c_" + e))
        for q in ['sp', 'pool']:
            for sl in range(NDMA_SLOTS):
                self.sems[('d', q, sl)] = self.sem_stack.enter_context(nc.semaphore(f"s_d_{q}_{sl}"))

    def barrier(self):
        toks = []
        for e in ['pe', 'act', 'dve', 'pool']:
            if self.cnt[e] > 0:
                toks.append((('c', e), self.cnt[e]))
        for q in ENGS:
            for t in self.dma_tok[q]:
                if t is not None:
                    toks.append(t)
        for e in ENGS:
            waits = []
            for (sem, val) in toks:
                if sem == ('c', e):
                    continue
                if self.seen[e].get(sem, 0) >= val:
                    continue
                waits.append((sem, val))
                self.seen[e][sem] = val
            if waits:
                self.ops[e].append((waits, None, None))
        self.last_w = {}
        self.readers = {}

    def _deps(self, eng, reads, writes):
        toks = []
        for r in reads:
            t = self.last_w.get(r)
            if t is not None:
                toks.append(t)
        for w in writes:
            t = self.last_w.get(w)
            if t is not None:
                toks.append(t)
            toks.extend(self.readers.get(w, []))
        need = {}
        for (sem, val) in toks:
            if not SAME_ENGINE_SYNC and sem == ('c', eng):
                continue
            if sem == ('c', 'pe') and eng == 'pe':
                continue
            if self.seen[eng].get(sem, 0) >= val:
                continue
            if need.get(sem, 0) < val:
                need[sem] = val
        for sem, val in need.items():
            self.seen[eng][sem] = val
        return list(need.items())

    def _commit(self, tok, reads, writes):
        for w in writes:
            self.last_w[w] = tok
            self.readers[w] = []
        for r in reads:
            if r in writes:
                continue
            self.readers.setdefault(r, []).append(tok)

    def op(self, eng, fn, reads=(), writes=()):
        self.nrec = getattr(self, 'nrec', 0) + 1
        if self.nrec > int(os.environ.get("MAXOPS", "100000000")):
            return None
        kp = getattr(self, 'key_prefix', '')
        reads = [r if r.startswith('ps') else kp + r for r in reads]
        writes = [w if w.startswith('ps') else kp + w for w in writes]
        pk = getattr(self, 'ps_prefix', '')
        reads = [('ps' + pk + r[2:]) if r.startswith('ps') else r for r in reads]
        writes = [('ps' + pk + w[2:]) if w.startswith('ps') else w for w in writes]
        writes = list(writes) + [r for r in reads if r.startswith('ps') and r not in writes]
        waits = self._deps(eng, reads, writes)
        self.cnt[eng] += 1
        tok = (('c', eng), self.cnt[eng])
        self.ops[eng].append((waits, fn, tok))
        self._commit(tok, reads, writes)
        return tok

    def dma(self, q, out, in_, reads=(), writes=(), final=False, **kw):
        self.nrec = getattr(self, 'nrec', 0) + 1
        if self.nrec > int(os.environ.get("MAXOPS", "100000000")):
            return None
        kp = getattr(self, 'key_prefix', '')
        reads = [kp + r for r in reads]
        writes = [kp + w for w in writes]
        waits = self._deps(q, reads, writes)
        n = self.dma_n[q]
        slot = n % NDMA_SLOTS
        prev = self.dma_tok[q][slot]
        if prev is not None and self.seen[q].get(prev[0], 0) < prev[1]:
            waits.append(prev)
            self.seen[q][prev[0]] = prev[1]
        tok = (('d', q, slot), 16 * (n // NDMA_SLOTS + 1))
        self.dma_n[q] += 1
        self.dma_tok[q][slot] = tok

        def fn(e, out=out, in_=in_, kw=kw):
            return e.dma_start(out=out, in_=in_, **kw)
        self.ops[q].append((waits, fn, tok))
        self._commit(tok, reads, writes)
        if final:
            self.final_tokens.append(tok)
        return tok

    def emit(self, last=True):
        nc = self.nc
        sems = self.sems
        with nc.Block() as block:
            final = list(self.final_tokens) if last else []

            def run(e, name):
                for waits, fn, tok in self.ops[name]:
                    for (s, v) in waits:
                        e.wait_ge(sems[s], v)
                    if fn is None:
                        continue
                    inst = fn(e)
                    inc = 16 if tok[0][0] == 'd' else 1
                    inst.then_inc(sems[tok[0]], inc)
                if name == 'sp':
                    for (s, v) in final:
                        e.wait_ge(sems[s], v)
                self.ops[name] = []

            @block.tensor
            def _(e):
                run(e, 'pe')

            @block.scalar
            def _(e):
                run(e, 'act')

            @block.vector
            def _(e):
                run(e, 'dve')

            @block.gpsimd
            def _(e):
                run(e, 'pool')

            @block.sync
            def _(e):
                run(e, 'sp')
        if last:
            self.sem_stack.close()


D = 1024
KC = 8
EPS = 1e-6


class K:
    def __init__(self, fused=False):
        self.nc = bass.Bass("TRN2", target_bir_lowering=False)
        self.st = ExitStack()
        self.P = Prog(self.nc)
        self.n = 0
        self.fused = fused
        self.io = {}
        self.pfx = ""

    def begin_phase(self, name, io):
        self.pfx = name + "_"
        self.io = io
        self.st = ExitStack()
        for a in ('wstage', 'rr_cache', 'identf', 'identb'):
            if hasattr(self, a):
                delattr(self, a)

    def scratch(self, name, shape, dt=F32):
        return self.nc.dram_tensor(name, list(shape), dt, kind="Internal").ap()

    def xin(self, name, arr_shape, dt=F32):
        return self.nc.dram_tensor(name, list(arr_shape), dt, kind="ExternalInput").ap()

    def xout(self, name, arr_shape, dt=F32):
        return self.nc.dram_tensor(name, list(arr_shape), dt, kind="ExternalOutput").ap()

    def din(self, name, shape, dt=F32):
        if self.fused:
            ap = self.io[name]
            assert list(ap.shape) == list(shape), (name, ap.shape, shape)
            return ap
        return self.nc.dram_tensor(name, list(shape), dt, kind="ExternalInput").ap()

    def dout(self, name, shape, dt=F32):
        if self.fused:
            ap = self.io[name]
            assert list(ap.shape) == list(shape), (name, ap.shape, shape)
            return ap
        return self.nc.dram_tensor(name, list(shape), dt, kind="ExternalOutput").ap()

    def sb(self, name, shape, dt=F32):
        pers = getattr(self, 'persist', None)
        if pers is not None and (self.pfx + name) in pers:
            return pers[self.pfx + name]
        return self.st.enter_context(self.nc.sbuf_tensor(self.pfx + name, list(shape), dt))

    def push_scope(self, persistent):
        self.persist = getattr(self, 'persist', None) or {}
        for (name, shape, dt) in persistent:
            self.persist[self.pfx + name] = self.st.enter_context(self.nc.sbuf_tensor(self.pfx + name, list(shape), dt))
        self._st_saved = self.st
        self.st = ExitStack()

    def pop_scope(self):
        self.P.barrier()
        self.P.emit(last=False)
        self.st.close()
        self.st = self._st_saved

    def ps(self, name, shape, dt=F32):
        return self.st.enter_context(self.nc.psum_tensor(self.pfx + name, list(shape), dt))

    def finish(self, last=True):
        if self.fused:
            self.P.barrier()
            self.P.emit(last=False)
            self.st.close()
            return None
        self.P.emit()
        self.st.close()
        return self.nc

    def finish_program(self):
        self.P.emit(last=True)
        return self.nc

    def mm(self, out, lhsT, rhs, start, stop, r, w):
        self.P.op('pe', lambda e: e.matmul(out, lhsT=lhsT, rhs=rhs, start=start, stop=stop), reads=r, writes=w)

    def tr(self, out, in_, ident, r, w):
        self.P.op('pe', lambda e: e.transpose(out=out, in_=in_, identity=ident), reads=list(r) + ['ident'], writes=w)

    def act(self, out, in_, func, r, w, **kw):
        self.P.op('act', lambda e: e.activation(out=out, in_=in_, func=func, **kw), reads=r, writes=w)

    def tt(self, eng, out, in0, in1, op, r, w):
        self.P.op(eng, lambda e: e.tensor_tensor(out=out, in0=in0, in1=in1, op=op), reads=r, writes=w)

    def ts(self, eng, out, in0, s1, s2, op0, op1, r, w):
        if op1 is None:
            self.P.op(eng, lambda e: e.tensor_scalar(out=out, in0=in0, scalar1=s1, scalar2=None, op0=op0), reads=r, writes=w)
        else:
            self.P.op(eng, lambda e: e.tensor_scalar(out=out, in0=in0, scalar1=s1, scalar2=s2, op0=op0, op1=op1), reads=r, writes=w)

    def stt(self, out, in0, scalar, in1, op0, op1, r, w):
        self.P.op('dve', lambda e: e.scalar_tensor_tensor(out=out, in0=in0, scalar=scalar, in1=in1, op0=op0, op1=op1),
                  reads=r, writes=w)

    def cp(self, eng, out, in_, r, w):
        if eng == 'act':
            self.P.op('act', lambda e: e.copy(out=out, in_=in_), reads=r, writes=w)
        else:
            self.P.op(eng, lambda e: e.tensor_copy(out=out, in_=in_), reads=r, writes=w)

    def recip(self, out, in_, r, w):
        self.P.op('dve', lambda e: e.reciprocal(out=out, in_=in_), reads=r, writes=w)

    def memset(self, eng, ap, val, w):
        self.P.op(eng, lambda e: e.memset(ap, val), reads=[], writes=w)

    def dma(self, q, out, in_, r=(), w=(), final=False, **kw):
        self.P.dma(q, out, in_, reads=r, writes=w, final=final, **kw)

    def consts(self, ident_d):
        self.identf = self.sb("identf", [128, 128], F32)
        self.identb = self.sb("identb", [128, 128], BF16)
        self.dma('sp', self.identf[:], ident_d, w=['ident'])
        self.cp('dve', self.identb[:], self.identf[:], ['ident'], ['ident'])

    def gain_cols(self, name, g_d):
        t = self.sb(name, [128, KC], F32)
        self.dma('sp', t[:], g_d.rearrange("(kc p) -> p kc", p=128), w=[name], allow_slow_non_contiguous=True)
        return t

    def bcast_row(self, name, vec_d, n):
        t = self.sb(name, [128, n], F32)
        self.dma('sp', t[:], vec_d.partition_broadcast(128), w=[name])
        return t

    def load_weight(self, name, w_d, kchunks, ncols, gcol=None, gkey=None, stage_cols=2048, q='sp'):
        wb = self.sb(name, [128, kchunks, ncols], BF16)
        if not hasattr(self, 'wstage'):
            self.wstage = [self.sb(f"wstage{i}", [128, stage_cols], F32) for i in range(2)]
            self.wstage_n = 0
            self.wstage_cols = stage_cols
        sc = self.wstage_cols
        wv = w_d.rearrange("(kc p) n -> p kc n", p=128)
        for kc in range(kchunks):
            for c0 in range(0, ncols, sc):
                cw = min(sc, ncols - c0)
                b = self.wstage_n % 2
                self.wstage_n += 1
                stg = self.wstage[b]
                self.dma(q, stg[:, 0:cw], wv[:, kc, c0:c0 + cw], w=[f'wstage{b}'])
                eng = 'act' if (kc % 2 == 0) else 'dve'
                if gcol is not None:
                    if eng == 'act':
                        self.act(wb[:, kc, c0:c0 + cw], stg[:, 0:cw], AF.Copy, [f'wstage{b}', gkey], [f'{name}{kc}'],
                                 scale=gcol[:, kc:kc + 1])
                    else:
                        self.ts('dve', wb[:, kc, c0:c0 + cw], stg[:, 0:cw], gcol[:, kc:kc + 1], None, ALU.mult, None,
                                [f'wstage{b}', gkey], [f'{name}{kc}'])
                else:
                    self.cp(eng, wb[:, kc, c0:c0 + cw], stg[:, 0:cw], [f'wstage{b}'], [f'{name}{kc}'])
        return wb

    def rstd_of(self, x_ap, xkey, ss, rstd, junk, key, ncols=D):
        self.act(junk, x_ap, AF.Square, [xkey], ['junk', key + 'ss'], accum_out=ss)
        self.ts('dve', rstd, ss, 1.0 / ncols, EPS, ALU.mult, ALU.add, [key + 'ss'], [key])
        self.act(rstd, rstd, AF.Sqrt, [key], [key])
        self.recip(rstd, rstd, [key], [key])


def pipeline(make_gen, n):
    active = []
    for i in range(n):
        for g in list(active):
            try:
                next(g)
            except StopIteration:
                active.remove(g)
        g = make_gen(i)
        active.append(g)
        try:
            next(g)
        except StopIteration:
            active.remove(g)
    while active:
        for g in list(active):
            try:
                next(g)
            except StopIteration:
                active.remove(g)


def pipeline_gen(make_gen, n):
    active = []
    for i in range(n):
        for g in list(active):
            try:
                next(g)
            except StopIteration:
                active.remove(g)
        g = make_gen(i)
        active.append(g)
        try:
            next(g)
        except StopIteration:
            active.remove(g)
        yield
    while active:
        for g in list(active):
            try:
                next(g)
            except StopIteration:
                active.remove(g)
        yield


def run_streams(k, streams):
    base_pfx = k.pfx
    gens = []
    for (pf, io, gf) in streams:
        gens.append([pf, io, None, gf])
    active = list(gens)
    while active:
        for st in list(active):
            pf, io, g, gf = st
            k.pfx = base_pfx + pf
            k.P.key_prefix = pf
            k.P.ps_prefix = pf
            k.io = io
            try:
                if g is None:
                    st[2] = gf(k)
                    g = st[2]
                next(g)
            except StopIteration:
                active.remove(st)
    k.pfx = base_pfx
    k.P.key_prefix = ''
    k.P.ps_prefix = ''


GELU_C = 1.5957691216057308


def norm_T(k, xt, xkey, xn, xnkey, xT_dst, xTkey, psT, psTkey, ss, rstd, junk, key, evac_eng='act'):
    k.rstd_of(xt, xkey, ss, rstd, junk, key)
    k.ts('dve', xn, xt, rstd, None, ALU.mult, None, [xkey, key], [xnkey])
    for kc in range(KC):
        k.tr(psT[:, kc * 128:(kc + 1) * 128], xn[:, kc * 128:(kc + 1) * 128], k.identb[:], [xnkey], [psTkey])
    k.cp(evac_eng, xT_dst, psT[:].rearrange("p (k t) -> p k t", k=KC), [psTkey], [xTkey])


def post_norm_res(k, ps2, pskeys, ht, hkey, gbc, gkey, tmp2, tmpkeys, ss2, rstd, junk, key):
    for j in range(2):
        k.act(junk[:, 0:512], ps2[j], AF.Square, [pskeys[j]], ['junk', key + f'ss{j}'], accum_out=ss2[:, j:j + 1])
    k.tt('dve', ss2[:, 0:1], ss2[:, 0:1], ss2[:, 1:2], ALU.add, [key + 'ss0', key + 'ss1'], [key + 'ss0'])
    k.ts('dve', rstd, ss2[:, 0:1], 1.0 / D, EPS, ALU.mult, ALU.add, [key + 'ss0'], [key])
    k.act(rstd, rstd, AF.Sqrt, [key], [key])
    k.recip(rstd, rstd, [key], [key])
    for j in range(2):
        sl = slice(j * 512, (j + 1) * 512)
        k.stt(tmp2[j], ps2[j], rstd, gbc[:, sl], ALU.mult, ALU.mult, [pskeys[j], key, gkey], [tmpkeys[j]])
        k.tt('pool', ht[:, sl], ht[:, sl], tmp2[j], ALU.add, [tmpkeys[j], hkey], [hkey])


def build_C1(NTOK, glu, k=None, ob_fm=False):
    k = k or K()
    NT = NTOK // 128
    oa = k.din("oa", [NTOK, 512])
    if ob_fm:
        obT = k.din("obT", [512, NTOK])
    else:
        ob = k.din("ob", [NTOK, 512])
    hin = k.din("hin", [NTOK, D])
    wout = k.din("wout", [D, D])
    g1 = k.din("g1", [D])
    ident_d = k.din("ident", [128, 128])
    if glu:
        wglu = k.din("wglu", [512, 512])
        bglu = k.din("bglu", [512])
    hout = k.dout("hout", [NTOK, D])
    k.consts(ident_d)
    g1bc = k.bcast_row("g1bc", g1, D)
    Wout = k.load_weight("Wout", wout, KC, D, stage_cols=1024)
    if glu:
        Wglu = k.load_weight("Wglu", wglu, 4, 512)
        bgbc = k.bcast_row("bgbc", bglu, 512)

    def ring(nm, shape, n, dt=F32):
        return [k.sb(f"{nm}{j}", shape, dt) for j in range(n)]
    oc = ring("oc", [128, D], 10 if glu else 4)
    ocb = ring("ocb", [128, D], 3, BF16)
    oT = ring("oT", [128, KC, 128], 3, BF16)
    ht = ring("ht", [128, D], 4)
    mix = ring("mix", [128, D], 5)
    tmp = ring("tmp", [128, D], 3)
    ss2 = ring("ss2", [128, 2], 4)
    rstd = ring("rstd", [128, 1], 5)
    junk = k.sb("junk", [128, D], BF16)
    if ob_fm:
        obt = ring("obt", [128, 4, 128], 4)
    if glu:
        yb = ring("yb", [128, 512], 3, BF16)
        yT = ring("yT", [128, 4, 128], 3, BF16)
        t1 = ring("t1", [128, 512], 9)
        zs = ring("zs", [128, 512], 4)
        psTg = k.ps("psTg", [128, D], BF16)
        psG = k.ps("psG", [128, 512])
    psTm = [k.ps(f"psTm{j}", [128, D], BF16) for j in range(2)]
    psM = [k.ps(f"psM{j}", [128, 512]) for j in range(4)]

    def tile(i):
        rows = slice(i * 128, (i + 1) * 128)
        def T(lst, nm):
            j = i % len(lst)
            return lst[j], f'{nm}{j}'
        oc_, koc = T(oc, 'oc'); ocb_, kocb = T(ocb, 'ocb'); oT_, koT = T(oT, 'oT'); ht_, kht = T(ht, 'ht')
        mix_, kmix = T(mix, 'mix'); tmp_, ktmp = T(tmp, 'tmp'); ss_, kss = T(ss2, 'ss2'); rs_, krs = T(rstd, 'rstd')
        pm = [psM[2 * (i % 2)], psM[2 * (i % 2) + 1]]
        kpm = [f'psM{2 * (i % 2)}', f'psM{2 * (i % 2) + 1}']
        ptm, kptm = psTm[i % 2], f'psTm{i % 2}'
        kA, kB = koc + 'A', koc + 'B'
        k.dma('sp', oc_[:, 0:512], oa[rows, :], w=[kA])
        if ob_fm:
            obt_, kobt = T(obt, 'obt')
            k.dma('sp', obt_[:], obT[:, rows].rearrange("(a p) t -> p a t", p=128), w=[kobt])
        else:
            k.dma('sp', oc_[:, 512:1024], ob[rows, :], w=[kB])
        yield
        if glu:
            y = oc_[:, 512:1024]
            yb_, kyb = T(yb, 'yb'); yT_, kyT = T(yT, 'yT'); t1_, kt1 = T(t1, 't1'); zs_, kzs = T(zs, 'zs')
            k.cp('dve', yb_[:], y, [kB], [kyb])
            k.act(t1_[:], y, AF.Square, [kB], [kt1])
            k.act(t1_[:], t1_[:], AF.Copy, [kt1], [kt1], scale=0.044715, bias=1.0)
            yield
            for kc in range(4):
                k.tr(psTg[:, kc * 128:(kc + 1) * 128], yb_[:, kc * 128:(kc + 1) * 128], k.identb[:], [kyb], ['psTg'])
            k.tt('pool', t1_[:], t1_[:], y, ALU.mult, [kt1, kB], [kt1])
            yield
            k.cp('act', yT_[:], psTg[:, 0:512].rearrange("p (k t) -> p k t", k=4), ['psTg'], [kyT])
            k.act(t1_[:], t1_[:], AF.Sigmoid, [kt1], [kt1], scale=GELU_C)
            yield
            for kc in range(4):
                k.mm(psG[:], yT_[:, kc, :], Wglu[:, kc, :], kc == 0, kc == 3, [kyT, f'Wglu{kc}'], ['psG'])
            yield
            k.tt('dve', zs_[:], psG[:], bgbc[:], ALU.add, ['psG', 'bgbc'], [kzs])
            yield
            k.act(zs_[:], zs_[:], AF.Sigmoid, [kzs], [kzs])
            yield
            k.tt('dve', zs_[:], t1_[:], zs_[:], ALU.mult, [kt1, kzs], [kzs])
            k.tt('dve', y, y, zs_[:], ALU.mult, [kB, kzs], [kB])
        if ob_fm:
            k.cp('dve', ocb_[:, 0:512], oc_[:, 0:512], [kA], [kocb])
            k.cp('pool', oT_[:, 4:8, :], obt_[:], [kobt], [koT + 'b'])
        else:
            k.cp('dve', ocb_[:], oc_[:], [kA, kB], [kocb])
        yield
        nk = 4 if ob_fm else KC
        for kc in range(nk):
            k.tr(ptm[:, kc * 128:(kc + 1) * 128], ocb_[:, kc * 128:(kc + 1) * 128], k.identb[:], [kocb], [kptm])
        yield
        k.cp('act', oT_[:, 0:nk, :], ptm[:, 0:nk * 128].rearrange("p (k t) -> p k t", k=nk), [kptm], [koT])
        yield
        for cg in range(2):
            for kc in range(KC):
                ok_ = (koT + 'b') if (ob_fm and kc >= 4) else koT
                k.mm(pm[cg][:], oT_[:, kc, :], Wout[:, kc, cg * 512:(cg + 1) * 512], kc == 0, kc == KC - 1,
                     [ok_, f'Wout{kc}'], [kpm[cg]])
        yield
        for j in range(2):
            k.act(junk[:, 0:512], pm[j][:], AF.Square, [kpm[j]], ['junk', kss], accum_out=ss_[:, j:j + 1])
        for j in range(2):
            k.cp('act', mix_[:, j * 512:(j + 1) * 512], pm[j][:], [kpm[j]], [kmix])
        k.dma('sp', ht_[:], hin[rows, :], w=[kht])
        yield
        k.tt('dve', ss_[:, 0:1], ss_[:, 0:1], ss_[:, 1:2], ALU.add, [kss], [kss])
        k.ts('dve', rs_[:], ss_[:, 0:1], 1.0 / D, EPS, ALU.mult, ALU.add, [kss], [krs])
        yield
        k.act(rs_[:], rs_[:], AF.Sqrt, [krs], [krs])
        yield
        k.recip(rs_[:], rs_[:], [krs], [krs])
        k.stt(tmp_[:], mix_[:], rs_[:], g1bc[:], ALU.mult, ALU.mult, [kmix, krs, 'g1bc'], [ktmp])
        yield
        k.tt('pool', ht_[:], ht_[:], tmp_[:], ALU.add, [kht, ktmp], [kht])
        k.dma('pool', hout[rows, :], ht_[:], r=[kht], final=True)

    pipeline(tile, NT)
    return k.finish()


def build_C3(NTOK, k=None):
    k = k or K()
    NB = NTOK // 512
    DFF = 4096
    FC = DFF // 128
    hin = k.din("hin", [NTOK, D])
    w1 = k.din("w1", [D, DFF])
    w2 = k.din("w2", [DFF, D])
    g4 = k.din("g4", [D])
    g5 = k.din("g5", [D])
    ident_d = k.din("ident", [128, 128])
    hout = k.dout("hout", [NTOK, D])
    k.consts(ident_d)
    g4c = k.gain_cols("g4c", g4)
    g5bc = k.bcast_row("g5bc", g5, D)
    W1 = k.load_weight("W1", w1, KC, DFF, gcol=g4c, gkey='g4c', stage_cols=512)
    W2 = k.load_weight("W2", w2, FC, D, stage_cols=512)
    ht = [k.sb(f"ht{i}", [128, D]) for i in range(4)]
    xn = [k.sb(f"xn{i}", [128, D], BF16) for i in range(2)]
    xT = k.sb("xT", [128, KC, 512], BF16)
    AT = k.sb("AT", [128, FC, 512], BF16)
    sq = [k.sb(f"sq{i}", [128, 512]) for i in range(2)]
    junk = k.sb("junk", [128, D], BF16)
    ss = [k.sb(f"ss{i}", [128, 1]) for i in range(2)]
    ss2 = [k.sb(f"ss2{i}", [128, 2]) for i in range(2)]
    rstd = [k.sb(f"rstd{i}", [128, 1]) for i in range(2)]
    rstd2 = [k.sb(f"rstdb{i}", [128, 1]) for i in range(2)]
    psT = k.ps("psT", [128, D], BF16)
    psU = [k.ps(f"psU{i}", [128, 512]) for i in range(3)]
    psD = [k.ps(f"psD{i}", [128, 512]) for i in range(4)]
    nu = 0
    for blk in range(NB):
        for tt in range(4):
            i = blk * 4 + tt
            b = i % 2
            rows = slice(i * 128, (i + 1) * 128)
            k.dma('sp', ht[tt][:], hin[rows, :], w=[f'ht{tt}'])
            norm_T(k, ht[tt][:], f'ht{tt}', xn[b][:], f'xn{b}', xT[:, :, tt * 128:(tt + 1) * 128], 'xT', psT[:], 'psT',
                   ss[b][:], rstd[b][:], junk[:], f'n{b}')
        for fc in range(FC):
            pu = nu % 3
            nu += 1
            for kc in range(KC):
                k.mm(psU[pu][:], W1[:, kc, fc * 128:(fc + 1) * 128], xT[:, kc, :], kc == 0, kc == KC - 1,
                     [f'W1{kc}', 'xT'], [f'psU{pu}'])
            sb_ = fc % 2
            k.act(sq[sb_][:], psU[pu][:], AF.Square, [f'psU{pu}'], [f'sq{sb_}'])
            k.stt(AT[:, fc, :], psU[pu][:], 0.0, sq[sb_][:], ALU.is_gt, ALU.mult, [f'psU{pu}', f'sq{sb_}'], ['AT'])
        for tt in range(4):
            i = blk * 4 + tt
            b = i % 2
            rows = slice(i * 128, (i + 1) * 128)
            for cg in range(2):
                pd = 2 * b + cg
                for fc in range(FC):
                    k.mm(psD[pd][:], AT[:, fc, tt * 128:(tt + 1) * 128], W2[:, fc, cg * 512:(cg + 1) * 512],
                         fc == 0, fc == FC - 1, ['AT', f'W2{fc}'], [f'psD{pd}'])
            post_norm_res(k, [psD[2 * b][:], psD[2 * b + 1][:]], [f'psD{2 * b}', f'psD{2 * b + 1}'], ht[tt], f'ht{tt}',
                          g5bc, 'g5bc', [sq[0][:], sq[1][:]], ['sq0', 'sq1'], ss2[b], rstd2[b][:], junk, f'pn{b}')
            k.dma('pool', hout[rows, :], ht[tt][:], r=[f'ht{tt}'], final=True)
    return k.finish()


def build_C2(NTOK, k=None):
    k = k or K()
    NB = NTOK // 512
    MEM = 256
    hin = k.din("hin", [NTOK, D])
    mem = k.din("mem", [MEM, D])
    wq = k.din("wq", [D, D])
    wk = k.din("wk", [D, D])
    wv = k.din("wv", [D, D])
    wo = k.din("wo", [D, D])
    g2 = k.din("g2", [D])
    g3 = k.din("g3", [D])
    g6 = k.din("g6", [D])
    ident_d = k.din("ident", [128, 128])
    hout = k.dout("hout", [NTOK, D])
    k.consts(ident_d)
    g2c = k.gain_cols("g2c", g2)
    g6c = k.gain_cols("g6c", g6)
    g3bc = k.bcast_row("g3bc", g3, D)
    Wk = k.load_weight("Wk", wk, KC, D, gcol=g6c, gkey='g6c', stage_cols=1024)
    Wv = k.load_weight("Wv", wv, KC, D, gcol=g6c, gkey='g6c', stage_cols=1024)
    Wq = k.load_weight("Wq", wq, KC, D, gcol=g2c, gkey='g2c', stage_cols=1024)
    Wo = k.load_weight("Wo", wo, KC, D, stage_cols=1024)
    ht = [k.sb(f"ht{i}", [128, D]) for i in range(2)]
    xn = [k.sb(f"xn{i}", [128, D], BF16) for i in range(2)]
    xT = [k.sb(f"xT{i}", [128, KC, 512], BF16) for i in range(2)]
    memT = k.sb("memT", [128, KC, MEM], BF16)
    KT = k.sb("KT", [128, KC, MEM], BF16)
    V = k.sb("V", [128, 2, D], BF16)
    QT = [k.sb(f"QT{i}", [128, KC, 512], BF16) for i in range(2)]
    Pm = [k.sb(f"Pm{i}", [128, 4, MEM], BF16) for i in range(3)]
    Pn = [k.sb(f"Pn{i}", [128, 4, MEM], BF16) for i in range(3)]
    PT = [k.sb(f"PT{i}", [128, 8, 128], BF16) for i in range(3)]
    OT = [k.sb(f"OT{i}", [128, KC, 128], BF16) for i in range(3)]
    tmp = [k.sb(f"tmp{i}", [128, 512]) for i in range(2)]
    junk = k.sb("junk", [128, D], BF16)
    ss = [k.sb(f"ss{i}", [128, 1]) for i in range(2)]
    ss2 = [k.sb(f"ss2{i}", [128, 2]) for i in range(2)]
    rstd = [k.sb(f"rstd{i}", [128, 1]) for i in range(2)]
    rstd2 = [k.sb(f"rstdb{i}", [128, 1]) for i in range(2)]
    mx = [k.sb(f"mx{i}", [128, 4]) for i in range(3)]
    sm = [k.sb(f"sm{i}", [128, 4]) for i in range(3)]
    psT = k.ps("psT", [128, D], BF16)
    psA = k.ps("psA", [128, 1024])
    psS = k.ps("psS", [128, 1024])
    psX = k.ps("psX", [128, 1024])
    for mt in range(2):
        k.dma('sp', ht[mt][:], mem[mt * 128:(mt + 1) * 128, :], w=[f'ht{mt}'])
        norm_T(k, ht[mt][:], f'ht{mt}', xn[mt][:], f'xn{mt}', memT[:, :, mt * 128:(mt + 1) * 128], 'memT', psT[:], 'psT',
               ss[mt][:], rstd[mt][:], junk[:], f'n{mt}')
    for cc in range(KC):
        pa = cc % 2
        for kc in range(KC):
            k.mm(psA[:, pa * 512:pa * 512 + MEM], Wk[:, kc, cc * 128:(cc + 1) * 128], memT[:, kc, :], kc == 0, kc == KC - 1,
                 [f'Wk{kc}', 'memT'], [f'psA{pa}'])
        k.cp('act' if cc % 2 else 'dve', KT[:, cc, :], psA[:, pa * 512:pa * 512 + MEM], [f'psA{pa}'], [f'KT{cc}'])
    for mt in range(2):
        for cg in range(2):
            for kc in range(KC):
                k.mm(psX[:, cg * 512:(cg + 1) * 512], memT[:, kc, mt * 128:(mt + 1) * 128], Wv[:, kc, cg * 512:(cg + 1) * 512],
                     kc == 0, kc == KC - 1, ['memT', f'Wv{kc}'], [f'psX{cg}'])
            k.cp('act' if cg else 'dve', V[:, mt, cg * 512:(cg + 1) * 512], psX[:, cg * 512:(cg + 1) * 512], [f'psX{cg}'], [f'V{mt}{cg}'])
    xt6 = [k.sb(f"xt6_{i}", [128, D]) for i in range(6)]
    ss6 = [k.sb(f"ss6_{i}", [128, 1]) for i in range(4)]
    rs6 = [k.sb(f"rs6_{i}", [128, 1]) for i in range(5)]
    xn3 = [k.sb(f"xn3_{i}", [128, D], BF16) for i in range(3)]
    psTx = k.ps("psTx", [128, D], BF16)

    def tile(i):
        blk, tt = divmod(i, 4)
        xb = blk % 2
        b = i % 3
        rows = slice(i * 128, (i + 1) * 128)
        tsl = slice(tt * 128, (tt + 1) * 128)
        def T(lst, nm):
            j = i % len(lst)
            return lst[j], f'{nm}{j}'
        xt_, kxt = T(xt6, 'xt6'); ss_, kss = T(ss6, 'ss6'); rs_, krs = T(rs6, 'rs6'); xn_, kxn = T(xn3, 'xn3')
        hb = i % 2
        k.dma('sp', xt_[:], hin[rows, :], w=[kxt])
        yield
        k.act(junk[:], xt_[:], AF.Square, [kxt], ['junk', kss], accum_out=ss_[:])
        yield
        k.ts('dve', rs_[:], ss_[:], 1.0 / D, EPS, ALU.mult, ALU.add, [kss], [krs])
        yield
        k.act(rs_[:], rs_[:], AF.Sqrt, [krs], [krs])
        yield
        k.recip(rs_[:], rs_[:], [krs], [krs])
        k.ts('dve', xn_[:], xt_[:], rs_[:], None, ALU.mult, None, [kxt, krs], [kxn])
        yield
        for kc in range(KC):
            k.tr(psTx[:, kc * 128:(kc + 1) * 128], xn_[:, kc * 128:(kc + 1) * 128], k.identb[:], [kxn], ['psTx'])
        yield
        k.cp('act', xT[xb][:, :, tsl], psTx[:].rearrange("p (k t) -> p k t", k=KC), ['psTx'], [f'xT{xb}'])
        yield
        if tt == 3:
            for cc in range(KC):
                pa = cc % 2
                for kc in range(KC):
                    k.mm(psA[:, pa * 512:(pa + 1) * 512], Wq[:, kc, cc * 128:(cc + 1) * 128], xT[xb][:, kc, :], kc == 0, kc == KC - 1,
                         [f'Wq{kc}', f'xT{xb}'], [f'psA{pa}'])
                k.cp('act' if cc % 2 else 'dve', QT[xb][:, cc, :], psA[:, pa * 512:(pa + 1) * 512], [f'psA{pa}'], [f'QT{xb}{cc}'])
        yield
        yield
        yield
        yield
        for h in range(4):
            sb_ = h // 2
            for j in range(2):
                cc = 2 * h + j
                k.mm(psS[:, h * MEM:(h + 1) * MEM], QT[xb][:, cc, tsl], KT[:, cc, :], j == 0, j == 1,
                     [f'QT{xb}{cc}', f'KT{cc}'], [f'psS{sb_}'])
        k.P.op('dve', lambda e, b=b: e.tensor_reduce(out=mx[b][:], in_=psS[:].rearrange("p (h m) -> p h m", h=4),
                                                    axis=AX.X, op=ALU.max),
               reads=['psS0', 'psS1'], writes=[f'mx{b}'])
        k.ts('dve', mx[b][:], mx[b][:], -1.0 / 16.0, None, ALU.mult, None, [f'mx{b}'], [f'mx{b}'])
        for h in range(4):
            k.act(Pm[b][:, h, :], psS[:, h * MEM:(h + 1) * MEM], AF.Exp, [f'psS{h // 2}', f'mx{b}'], [f'Pm{b}', f'sm{b}'],
                  scale=1.0 / 16.0, bias=mx[b][:, h:h + 1], accum_out=sm[b][:, h:h + 1])
        k.recip(sm[b][:], sm[b][:], [f'sm{b}'], [f'sm{b}'])
        k.tt('dve', Pn[b][:], Pm[b][:], sm[b][:].unsqueeze(2).broadcast_to([128, 4, MEM]), ALU.mult,
             [f'Pm{b}', f'sm{b}'], [f'Pn{b}'])
        yield
        for h in range(4):
            for mt in range(2):
                k.tr(psT[:, (h * 2 + mt) * 128:(h * 2 + mt + 1) * 128], Pn[b][:, h, mt * 128:(mt + 1) * 128], k.identb[:],
                     [f'Pn{b}'], ['psT'])
        k.cp('act', PT[b][:], psT[:].rearrange("p (k t) -> p k t", k=8), ['psT'], [f'PT{b}'])
        for cc in range(KC):
            h = cc // 2
            pa = cc // 4
            for mt in range(2):
                k.mm(psA[:, cc * 128:(cc + 1) * 128], V[:, mt, cc * 128:(cc + 1) * 128], PT[b][:, h * 2 + mt, :],
                     mt == 0, mt == 1, [f'V{mt}{cc // 4}', f'PT{b}'], [f'psA{pa}'])
        k.cp('dve', OT[b][:, 0:4, :], psA[:, 0:512].rearrange("p (k t) -> p k t", k=4), ['psA0'], [f'OT{b}_0'])
        k.cp('act', OT[b][:, 4:8, :], psA[:, 512:1024].rearrange("p (k t) -> p k t", k=4), ['psA1'], [f'OT{b}_1'])
        k.dma('sp', ht[hb][:], hin[rows, :], w=[f'ht{hb}'])
        yield
        for cg in range(2):
            for cc in range(KC):
                k.mm(psX[:, cg * 512:(cg + 1) * 512], OT[b][:, cc, :], Wo[:, cc, cg * 512:(cg + 1) * 512],
                     cc == 0, cc == KC - 1, [f'OT{b}_{cc // 4}', f'Wo{cc}'], [f'psX{cg}'])
        post_norm_res(k, [psX[:, 0:512], psX[:, 512:1024]], ['psX0', 'psX1'], ht[hb], f'ht{hb}',
                      g3bc, 'g3bc', [tmp[0][:], tmp[1][:]], ['tmp0', 'tmp1'], ss2[b % 2], rstd2[b % 2][:], junk, f'pn{b % 2}')
        k.dma('pool', hout[rows, :], ht[hb][:], r=[f'ht{hb}'], final=True)

    pipeline(tile, NTOK // 128)
    return k.finish()


def build_A2(NTOK, NC, fm, NF, k=None):
    k = k or K()
    NB = NTOK // 512
    x = k.din("x", [NTOK, D])
    gain = k.din("gain", [D])
    W = k.din("W", [D, NC])
    ident_d = k.din("ident", [128, 128])
    out = k.dout("out", [NTOK, NC])
    outT = k.dout("outT", [NF, NTOK])
    k.consts(ident_d)
    gc = k.gain_cols("gc", gain)
    Wb = k.load_weight("Wb", W, KC, NC, gcol=gc, gkey='gc', stage_cols=1408)
    cgs = [(c0, min(512, NC - c0)) for c0 in range(0, NC, 512)]
    def ring(nm, shape, n, dt=F32):
        return [k.sb(f"{nm}{j}", shape, dt) for j in range(n)]
    xt = ring("xt", [128, D], 6)
    xn = ring("xn", [128, D], 3, BF16)
    xT = [k.sb(f"xT{i}", [128, KC, 512], BF16) for i in range(2)]
    ot = [k.sb(f"ot{i}", [128, NC]) for i in range(2)]
    ft = [k.sb(f"ft{i}", [128, 512]) for i in range(2)]
    junk = k.sb("junk", [128, D], BF16)
    ss = ring("ss", [128, 1], 4)
    rstd = ring("rstd", [128, 1], 5)
    psT = k.ps("psT", [128, D], BF16)
    psO = [k.ps(f"psO{i}", [128, 512]) for i in range(4)]
    psF = [k.ps(f"psF{i}", [128, 512]) for i in range(2)]
    cnt = {'no': 0, 'nf': 0}

    def tile(i):
        blk, tt = divmod(i, 4)
        xb = blk % 2
        def T(lst, nm):
            j = i % len(lst)
            return lst[j], f'{nm}{j}'
        xt_, kxt = T(xt, 'xt'); xn_, kxn = T(xn, 'xn'); ss_, kss = T(ss, 'ss'); rs_, krs = T(rstd, 'rstd')
        k.dma('sp', xt_[:], x[i * 128:(i + 1) * 128, :], w=[kxt])
        yield
        k.act(junk[:], xt_[:], AF.Square, [kxt], ['junk', kss], accum_out=ss_[:])
        yield
        k.ts('dve', rs_[:], ss_[:], 1.0 / D, EPS, ALU.mult, ALU.add, [kss], [krs])
        yield
        k.act(rs_[:], rs_[:], AF.Sqrt, [krs], [krs])
        yield
        k.recip(rs_[:], rs_[:], [krs], [krs])
        k.ts('dve', xn_[:], xt_[:], rs_[:], None, ALU.mult, None, [kxt, krs], [kxn])
        yield
        for kc in range(KC):
            k.tr(psT[:, kc * 128:(kc + 1) * 128], xn_[:, kc * 128:(kc + 1) * 128], k.identb[:], [kxn], ['psT'])
        yield
        k.cp('act', xT[xb][:, :, tt * 128:(tt + 1) * 128], psT[:].rearrange("p (k t) -> p k t", k=KC), ['psT'], [f'xT{xb}'])
        yield
        if tt != 3:
            return
        for t2 in range(4):
            i2 = blk * 4 + t2
            b = i2 % 2
            for ci, (c0, cw) in enumerate(cgs):
                pb = cnt['no'] % 4
                cnt['no'] += 1
                for kc in range(KC):
                    k.mm(psO[pb][:, 0:cw], xT[xb][:, kc, t2 * 128:(t2 + 1) * 128], Wb[:, kc, c0:c0 + cw], kc == 0, kc == KC - 1,
                         [f'xT{xb}', f'Wb{kc}'], [f'psO{pb}'])
                k.cp('dve' if pb % 2 == 0 else 'act', ot[b][:, c0:c0 + cw], psO[pb][:, 0:cw], [f'psO{pb}'], [f'ot{b}_{pb % 2}'])
            k.dma('pool', out[i2 * 128:(i2 + 1) * 128, :], ot[b][:], r=[f'ot{b}_0', f'ot{b}_1'], final=True)
        for (c0, cw, r0) in fm:
            pf = cnt['nf'] % 2
            cnt['nf'] += 1
            for kc in range(KC):
                k.mm(psF[pf][0:cw, :], Wb[:, kc, c0:c0 + cw], xT[xb][:, kc, :], kc == 0, kc == KC - 1,
                     [f'Wb{kc}', f'xT{xb}'], [f'psF{pf}'])
            k.cp('dve' if pf == 0 else 'act', ft[pf][0:cw, :], psF[pf][0:cw, :], [f'psF{pf}'], [f'ft{pf}'])
            k.dma('pool', outT[r0:r0 + cw, blk * 512:(blk + 1) * 512], ft[pf][0:cw, :], r=[f'ft{pf}'], final=True)

    pipeline(tile, NTOK // 128)
    return k.finish()


def gen_GLA(L, k):
    NT = L // 128
    qT = k.din("qT", [128, L])
    kT = k.din("kT", [128, L])
    ktok = k.din("ktok", [L, 128])
    v = k.din("v", [L, 256])
    gate = k.din("gate", [L, 256])
    dlrT = k.din("dlrT", [16, L])
    w2 = k.din("w2", [16, 128])
    bdec = k.din("bdec", [1, 128])
    gn = k.din("gn", [256])
    triu_d = k.din("triu", [128, 128])
    trigt_d = k.din("trigt", [128, 128])
    oa = k.dout("oa", [L, 256])

    triu = k.sb("triu_s", [128, 128])
    trigt = k.sb("trigt_s", [128, 128])
    k.dma('sp', triu[:], triu_d, w=['triu'])
    k.dma('sp', trigt[:], trigt_d, w=['trigt'])
    w2s = k.sb("w2s", [16, 128])
    k.dma('sp', w2s[:], w2, w=['w2s'])
    bds = k.sb("bds", [1, 128])
    k.dma('sp', bds[:], bdec, w=['bds'])
    ones1 = k.sb("ones1", [1, 128])
    k.memset('dve', ones1[:], 1.0, ['ones1'])
    gnbc = k.bcast_row("gnbc", gn, 256)
    S = k.sb("S", [128, 128], mybir.dt.float32r)
    zS = k.sb("zS", [128, 128])
    k.memset('dve', zS[:], 0.0, ['zS'])
    k.cp('dve', S[:], zS[:], ['zS'], ['S'])
    rm = k.sb("rm", [128, 2])
    k.memset('dve', rm[:], 0.0, ['rm'])
    k.memset('dve', rm[0:64, 0:1], 0.125, ['rm'])
    k.memset('dve', rm[64:128, 1:2], 0.125, ['rm'])

    def ring(nm, shape, n, dt=F32):
        return [k.sb(f"{nm}{j}", shape, dt) for j in range(n)]
    FR_ = mybir.dt.float32r
    triur = k.sb("triur", [128, 128], FR_)
    trigtr = k.sb("trigtr", [128, 128], FR_)
    k.cp('dve', triur[:], triu[:], ['triu'], ['triur'])
    k.cp('dve', trigtr[:], trigt[:], ['trigt'], ['trigtr'])
    vr = ring("vr", [128, 256], 10, FR_)
    qTt, kTt, kt, gt = ring("qTt", [128, 128], 8), ring("kTt", [128, 128], 8), ring("kt", [128, 128], 8), ring("gt", [128, 256], 8)
    vt = ring("vt", [128, 256], 11)
    dt_ = ring("dt", [16, 128], 3)
    la = ring("la", [128, 128], 4, mybir.dt.float32r)
    sg = ring("sg", [128, 256], 16)
    EqT, EkT, Eks = ring("EqT", [128, 128], 7), ring("EkT", [128, 128], 3), ring("Eks", [128, 128], 3)
    qin, kin, kst = ring("qin", [128, 2, 128], 5, mybir.dt.float32r), ring("kin", [128, 128], 3, mybir.dt.float32r), ring("kst", [128, 128], 5, mybir.dt.float32r)
    sc0, sc1 = ring("sc0_", [128, 128], 3, mybir.dt.float32r), ring("sc1_", [128, 128], 3, mybir.dt.float32r)
    osr = ring("osr", [128, 256], 6)
    osb = ring("osb", [128, 256], 3)
    ss, rs = ring("ss", [128, 2], 4), ring("rs", [128, 2], 5)
    ot = ring("ot", [128, 256], 3)
    junk = k.sb("junk", [128, 128])
    psZ = [k.ps(f"psZ{j}", [128, 512]) for j in range(2)]
    psA = [k.ps(f"psA{j}", [128, 512]) for j in range(2)]
    psB = [k.ps(f"psB{j}", [128, 512]) for j in range(2)]
    psC = [k.ps(f"psC{j}", [128, 512]) for j in range(2)]

    def tile(i):
        rows = slice(i * 128, (i + 1) * 128)
        R = lambda lst: (lst[i % len(lst)], f'{lst[0].name if hasattr(lst[0], "name") else id(lst)}_{i % len(lst)}')
        def T(lst, nm):
            j = i % len(lst)
            return lst[j], f'{nm}{j}'
        q_, kq = T(qTt, 'qTt'); kT_, kkT = T(kTt, 'kTt'); kt_, kkt = T(kt, 'kt'); v_, kv = T(vt, 'vt'); g_, kg = T(gt, 'gt')
        d_, kd = T(dt_, 'dt'); la_, kla = T(la, 'la'); sg_, ksg = T(sg, 'sg')
        Eq, kEq = T(EqT, 'EqT'); Ek, kEk = T(EkT, 'EkT'); Es, kEs = T(Eks, 'Eks')
        qi, kqi = T(qin, 'qin'); ki, kki = T(kin, 'kin'); ks, kks = T(kst, 'kst')
        scs = [T(sc0, 'sc0_'), T(sc1, 'sc1_')]
        orw, korw = T(osr, 'osr'); ob_, kob = T(osb, 'osb'); ss_, kss = T(ss, 'ss'); rs_, krs = T(rs, 'rs'); ot_, kot = T(ot, 'ot')
        pz, kpz = psZ[i % 2], f'psZ{i % 2}'
        pa, kpa = psA[i % 2], f'psA{i % 2}'
        pb, kpb = psB[i % 2], f'psB{i % 2}'
        pc, kpc = psC[i % 2], f'psC{i % 2}'
        k.dma('sp', q_[:], qT[:, rows], w=[kq])
        k.dma('sp', kT_[:], kT[:, rows], w=[kkT])
        k.dma('sp', kt_[:], ktok[rows, :], w=[kkt])
        k.dma('sp', v_[:], v[rows, :], w=[kv])
        k.dma('sp', g_[:], gate[rows, :], w=[kg])
        k.dma('sp', d_[:], dlrT[:, rows], w=[kd])
        yield
        k.mm(pz[:, 0:128], d_[:], w2s[:], True, False, [kd, 'w2s'], [kpz])
        k.mm(pz[:, 0:128], ones1[:], bds[:], False, True, ['ones1', 'bds'], [kpz])
        yield
        k.act(la_[:], pz[:, 0:128], AF.Exp, [kpz], [kla], scale=-1.0)
        k.act(la_[:], la_[:].bitcast(F32), AF.Ln, [kla], [kla], bias=1.0)
        k.act(sg_[:], g_[:], AF.Exp, [kg], [ksg], scale=-1.0)
        vr_, kvr = T(vr, 'vr')
        k.cp('act', vr_[:], v_[:], [kv], [kvr])
        yield
        k.ts('dve', la_[:], la_[:].bitcast(F32), -1.0 / 16.0, None, ALU.mult, None, [kla], [kla])
        k.ts('dve', sg_[:], sg_[:], 1.0, None, ALU.add, None, [ksg], [ksg])
        k.recip(sg_[:], sg_[:], [ksg], [ksg])
        yield
        k.mm(pa[:, 0:128], la_[:], triur[:], True, True, [kla, 'triur'], [kpa])
        k.mm(pa[:, 128:256], trigtr[:], la_[:], True, True, [kla, 'trigtr'], [kpa])
        yield
        k.act(Eq[:], pa[:, 0:128], AF.Exp, [kpa], [kEq])
        k.act(Ek[:], pa[:, 0:128], AF.Exp, [kpa], [kEk], scale=-1.0)
        k.act(Es[:], pa[:, 128:256], AF.Exp, [kpa], [kEs])
        yield
        for h in range(2):
            k.stt(qi[:, h, :], q_[:], rm[:, h:h + 1], Eq[:], ALU.mult, ALU.mult, [kq, kEq, 'rm'], [kqi])
        k.tt('pool', ki[:], kT_[:], Ek[:], ALU.mult, [kkT, kEk], [kki])
        k.tt('pool', ks[:], kt_[:], Es[:], ALU.mult, [kkt, kEs], [kks])
        k.tt('pool', sg_[:], sg_[:], g_[:], ALU.mult, [ksg, kg], [ksg])
        yield
        for h in range(2):
            hp = slice(h * 64, (h + 1) * 64)
            k.mm(pb[:, h * 128:(h + 1) * 128], ki[:], qi[:, h, :], True, True, [kki, kqi], [kpb])
        yield
        for h in range(2):
            k.tt('dve', scs[h][0][:], pb[:, h * 128:(h + 1) * 128], triu[:], ALU.mult, [kpb, 'triu'], [scs[h][1]])
        yield
        for h in range(2):
            hp = slice(h * 64, (h + 1) * 64)
            k.mm(pc[:, h * 128:(h + 1) * 128], scs[h][0][:], vr_[:, h * 128:(h + 1) * 128], True, False, [scs[h][1], kvr], [kpc])
            k.mm(pc[:, h * 128:(h + 1) * 128], qi[:, h, :], S[:], False, True, [kqi, 'S'], [kpc])
        k.mm(pc[:, 256:512], ks[:], vr_[:], True, True, [kks, kvr], [kpc])
        yield
        for h in range(2):
            hp = slice(h * 64, (h + 1) * 64)
            k.stt(S[hp, :], S[hp, :].bitcast(F32), Eq[hp, 127:128], pc[hp, 256 + h * 128:256 + (h + 1) * 128], ALU.mult, ALU.add,
                  ['S', kEq, kpc], ['S'])
        k.cp('act', orw[:], pc[:, 0:256], [kpc], [korw])
        yield
        for h in range(2):
            k.act(junk[:], orw[:, h * 128:(h + 1) * 128], AF.Square, [korw], ['junk', kss], accum_out=ss_[:, h:h + 1])
        yield
        k.ts('dve', rs_[:], ss_[:], 1.0 / 128.0, EPS, ALU.mult, ALU.add, [kss], [krs])
        yield
        k.act(rs_[:], rs_[:], AF.Ln, [krs], [krs])
        k.act(rs_[:], rs_[:], AF.Exp, [krs], [krs], scale=-0.5)
        yield
        for h in range(2):
            hs = slice(h * 128, (h + 1) * 128)
            k.stt(ob_[:, hs], orw[:, hs], rs_[:, h:h + 1], gnbc[:, hs], ALU.mult, ALU.mult, [korw, krs, 'gnbc'], [kob])
        yield
        k.tt('pool', ot_[:], ob_[:], sg_[:], ALU.mult, [kob, ksg], [kot])
        k.dma('pool', oa[rows, :], ot_[:], r=[kot], final=True)

    yield from pipeline_gen(tile, NT)


def build_GLA(L, k=None):
    k = k or K()
    for _ in gen_GLA(L, k):
        pass
    return k.finish()


TWO_PI = 2.0 * math.pi
C1 = 6.28125
C2 = TWO_PI - 6.28125
PI_LO = 3.1415925


def range_sincos(k, x, xkey, shape, s_out, c_out, skey, ckey, pfx):
    if not hasattr(k, 'rr_cache'):
        k.rr_cache = {}
    if pfx not in k.rr_cache:
        k.rr_cache[pfx] = (k.sb(pfx + "kf", shape), k.sb(pfx + "ki", shape, I32), k.sb(pfx + "r", shape), k.sb(pfx + "m", shape))
    kf, ki, r, m = k.rr_cache[pfx]
    a = lambda t: t[:]
    K1, K2, K3, K4 = pfx + 'kf', pfx + 'ki', pfx + 'r', pfx + 'm'
    k.ts('dve', a(kf), x, 1.0 / TWO_PI, None, ALU.mult, None, [xkey], [K1])
    k.cp('dve', a(ki), a(kf), [K1], [K2])
    k.cp('dve', a(kf), a(ki), [K2], [K1])
    k.stt(a(r), a(kf), -C1, x, ALU.mult, ALU.add, [K1, xkey], [K3])
    k.stt(a(r), a(kf), -C2, a(r), ALU.mult, ALU.add, [K1, K3], [K3])
    k.ts('dve', a(m), a(r), math.pi, -TWO_PI, ALU.is_gt, ALU.mult, [K3], [K4])
    k.tt('dve', a(r), a(r), a(m), ALU.add, [K3, K4], [K3])
    k.ts('dve', a(m), a(r), -math.pi, TWO_PI, ALU.is_lt, ALU.mult, [K3], [K4])
    k.tt('dve', a(r), a(r), a(m), ALU.add, [K3, K4], [K3])
    k.ts('dve', a(kf), a(r), PI_LO, -PI_LO, ALU.min, ALU.max, [K3], [K1])
    k.act(s_out, a(kf), AF.Sin, [K1], [skey])
    k.ts('dve', a(r), a(r), math.pi / 2, None, ALU.add, None, [K3], [K3])
    k.ts('dve', a(m), a(r), math.pi, -TWO_PI, ALU.is_gt, ALU.mult, [K3], [K4])
    k.tt('dve', a(r), a(r), a(m), ALU.add, [K3, K4], [K3])
    k.ts('dve', a(kf), a(r), PI_LO, -PI_LO, ALU.min, ALU.max, [K3], [K1])
    k.act(c_out, a(kf), AF.Sin, [K1], [ckey])


def gen_S5(L, k):
    NT = L // 128
    NS = 1024
    uT = k.din("uT", [256, L])
    u = k.din("u", [L, 256])
    lam_re = k.din("lam_re", [NS])
    lam_im = k.din("lam_im", [NS])
    lstep = k.din("lstep", [NS])
    Bre = k.din("Bre", [2, 128, 512])
    Bim = k.din("Bim", [2, 128, 512])
    Cre = k.din("Cre", [8, 128, 32])
    Cim = k.din("Cim", [8, 128, 32])
    dsk = k.din("dsk", [256])
    triu_d = k.din("triu", [128, 128])
    iop_d = k.din("iota_p", [128, 1])
    iof_d = k.din("iota_f", [128, 128])
    y = k.dout("y", [L, 256])

    k.push_scope([("triu_s", [128, 128], F32), ("dbc", [128, 256], F32), ("BBr", [128, 2, 512], mybir.dt.float32r), ("BBi", [128, 2, 512], mybir.dt.float32r),
                  ("Pr", [128, NS], F32), ("Pi", [128, NS], F32), ("Qr", [128, 8, 128], F32), ("Qi", [128, 8, 128], F32),
                  ("L128r", [128, 8], F32), ("L128i", [128, 8], F32), ("Cr", [128, 8, 32], F32), ("nCi", [128, 8, 32], F32),
                  ("car_r", [128, 8], F32), ("car_i", [128, 8], F32), ("ntriu", [128, 128], mybir.dt.float32r), ("nCr", [128, 8, 32], mybir.dt.float32r), ("triur", [128, 128], mybir.dt.float32r), ("Crr", [128, 8, 32], mybir.dt.float32r), ("nCir", [128, 8, 32], mybir.dt.float32r)])
    triu = k.sb("triu_s", [128, 128])
    k.dma('sp', triu[:], triu_d, w=['triu'])
    iop = k.sb("iop", [128, 1])
    k.dma('sp', iop[:], iop_d, w=['iop'])
    negp = k.sb("negp", [128, 1])
    k.ts('dve', negp[:], iop[:], -1.0, None, ALU.mult, None, ['iop'], ['negp'])
    iof = k.sb("iof", [128, 128])
    k.dma('sp', iof[:], iof_d, w=['iof'])
    dbc = k.bcast_row("dbc", dsk, 256)
    R = [128, NS]
    lr = k.bcast_row("lr", lam_re, NS)
    li = k.bcast_row("li", lam_im, NS)
    dl = k.bcast_row("dl", lstep, NS)
    k.ts('dve', lr[:], lr[:], -1e-4, None, ALU.min, None, ['lr'], ['lr'])
    k.act(dl[:], dl[:], AF.Exp, ['dl'], ['dl'])
    a_ = k.sb("a_", R)
    th = k.sb("th", R)
    k.tt('dve', a_[:], lr[:], dl[:], ALU.mult, ['lr', 'dl'], ['a_'])
    k.tt('dve', th[:], li[:], dl[:], ALU.mult, ['li', 'dl'], ['th'])
    sn = k.sb("sn", R)
    cs = k.sb("cs", R)
    range_sincos(k, th[:], 'th', R, sn[:], cs[:], 'sn', 'cs', 'rr_')
    ea = k.sb("ea", R)
    k.act(ea[:], a_[:], AF.Exp, ['a_'], ['ea'])
    nr = k.sb("nr", R)
    ni = k.sb("ni", R)
    k.tt('dve', nr[:], ea[:], cs[:], ALU.mult, ['ea', 'cs'], ['nr'])
    k.ts('dve', nr[:], nr[:], -1.0, None, ALU.add, None, ['nr'], ['nr'])
    k.tt('dve', ni[:], ea[:], sn[:], ALU.mult, ['ea', 'sn'], ['ni'])
    den = k.sb("den", R)
    t0 = k.sb("t0", R)
    k.tt('dve', den[:], lr[:], lr[:], ALU.mult, ['lr'], ['den'])
    k.tt('dve', t0[:], li[:], li[:], ALU.mult, ['li'], ['t0'])
    k.tt('dve', den[:], den[:], t0[:], ALU.add, ['den', 't0'], ['den'])
    k.recip(den[:], den[:], ['den'], ['den'])
    gr = k.sb("gr", R)
    gi = k.sb("gi", R)
    k.tt('dve', gr[:], nr[:], lr[:], ALU.mult, ['nr', 'lr'], ['gr'])
    k.tt('dve', t0[:], ni[:], li[:], ALU.mult, ['ni', 'li'], ['t0'])
    k.tt('dve', gr[:], gr[:], t0[:], ALU.add, ['gr', 't0'], ['gr'])
    k.tt('dve', gr[:], gr[:], den[:], ALU.mult, ['gr', 'den'], ['gr'])
    k.tt('dve', gi[:], ni[:], lr[:], ALU.mult, ['ni', 'lr'], ['gi'])
    k.tt('dve', t0[:], nr[:], li[:], ALU.mult, ['nr', 'li'], ['t0'])
    k.tt('dve', gi[:], gi[:], t0[:], ALU.subtract, ['gi', 't0'], ['gi'])
    k.tt('dve', gi[:], gi[:], den[:], ALU.mult, ['gi', 'den'], ['gi'])
    Br = k.sb("Br", [128, 2, 512])
    Bi = k.sb("Bi", [128, 2, 512])
    BBr = k.sb("BBr", [128, 2, 512])
    BBi = k.sb("BBi", [128, 2, 512])
    for hc in range(2):
        k.dma('sp', Br[:, hc, :], Bre[hc], w=[f'Br{hc}'])
        k.dma('sp', Bi[:, hc, :], Bim[hc], w=[f'Bi{hc}'])
    grv = gr[:].rearrange("p (h n) -> p h n", h=2)
    giv = gi[:].rearrange("p (h n) -> p h n", h=2)
    t0v = t0[:].rearrange("p (h n) -> p h n", h=2)
    BK = ['Br0', 'Br1', 'Bi0', 'Bi1']
    k.tt('dve', BBr[:], grv, Br[:], ALU.mult, ['gr'] + BK, ['BBr'])
    k.tt('dve', t0v, giv, Bi[:], ALU.mult, ['gi'] + BK, ['t0'])
    k.tt('dve', BBr[:], BBr[:].bitcast(F32), t0v, ALU.subtract, ['BBr', 't0'], ['BBr'])
    k.tt('dve', BBi[:], grv, Bi[:], ALU.mult, ['gr'] + BK, ['BBi'])
    k.tt('dve', t0v, giv, Br[:], ALU.mult, ['gi'] + BK, ['t0'])
    k.tt('dve', BBi[:], BBi[:].bitcast(F32), t0v, ALU.add, ['BBi', 't0'], ['BBi'])
    ang = k.sb("ang", R)
    k.ts('dve', ang[:], th[:], iop[:, 0:1], None, ALU.mult, None, ['th', 'iop'], ['ang'])
    Pr = k.sb("Pr", R)
    Pi = k.sb("Pi", R)
    range_sincos(k, ang[:], 'ang', R, sn[:], cs[:], 'sn', 'cs', 'rr_')
    k.act(ea[:], a_[:], AF.Exp, ['a_', 'negp'], ['ea'], scale=negp[:, 0:1])
    k.tt('dve', Pr[:], ea[:], cs[:], ALU.mult, ['ea', 'cs'], ['Pr'])
    k.stt(Pi[:], ea[:], -1.0, sn[:], ALU.mult, ALU.mult, ['ea', 'sn'], ['Pi'])
    Cs = [128, 8]
    lrc = k.sb("lrc", Cs)
    lic = k.sb("lic", Cs)
    dlc = k.sb("dlc", Cs)
    cv = lambda d: d.rearrange("(blk p) -> p blk", p=128)
    k.dma('sp', lrc[:], cv(lam_re), w=['lrc'], allow_slow_non_contiguous=True)
    k.dma('sp', lic[:], cv(lam_im), w=['lic'], allow_slow_non_contiguous=True)
    k.dma('sp', dlc[:], cv(lstep), w=['dlc'], allow_slow_non_contiguous=True)
    k.ts('dve', lrc[:], lrc[:], -1e-4, None, ALU.min, None, ['lrc'], ['lrc'])
    k.act(dlc[:], dlc[:], AF.Exp, ['dlc'], ['dlc'])
    ac = k.sb("ac", Cs)
    thc = k.sb("thc", Cs)
    k.tt('dve', ac[:], lrc[:], dlc[:], ALU.mult, ['lrc', 'dlc'], ['ac'])
    k.tt('dve', thc[:], lic[:], dlc[:], ALU.mult, ['lic', 'dlc'], ['thc'])
    Qr = k.sb("Qr", [128, 8, 128])
    Qi = k.sb("Qi", [128, 8, 128])
    angv = ang[:].rearrange("p (b t) -> p b t", b=8)
    eav = ea[:].rearrange("p (b t) -> p b t", b=8)
    for blk in range(8):
        k.ts('dve', angv[:, blk, :], iof[:], thc[:, blk:blk + 1], None, ALU.mult, None, ['iof', 'thc'], ['ang'])
    range_sincos(k, ang[:], 'ang', R, sn[:], cs[:], 'sn', 'cs', 'rr_')
    for blk in range(8):
        k.act(eav[:, blk, :], iof[:], AF.Exp, ['iof', 'ac'], ['ea'], scale=ac[:, blk:blk + 1])
    k.tt('dve', Qr[:].rearrange("p b t -> p (b t)"), ea[:], cs[:], ALU.mult, ['ea', 'cs'], ['Qr'])
    k.tt('dve', Qi[:].rearrange("p b t -> p (b t)"), ea[:], sn[:], ALU.mult, ['ea', 'sn'], ['Qi'])
    a128 = k.sb("a128", Cs)
    s128 = k.sb("s128", Cs)
    c128 = k.sb("c128", Cs)
    L128r = k.sb("L128r", Cs)
    L128i = k.sb("L128i", Cs)
    k.ts('dve', a128[:], thc[:], 128.0, None, ALU.mult, None, ['thc'], ['a128'])
    range_sincos(k, a128[:], 'a128', Cs, s128[:], c128[:], 's128', 'c128', 'rc_')
    k.act(a128[:], ac[:], AF.Exp, ['ac', 's128', 'c128'], ['a128'], scale=128.0)
    k.tt('dve', L128r[:], a128[:], c128[:], ALU.mult, ['a128', 'c128'], ['L128r'])
    k.tt('dve', L128i[:], a128[:], s128[:], ALU.mult, ['a128', 's128'], ['L128i'])
    Cr = k.sb("Cr", [128, 8, 32])
    nCi = k.sb("nCi", [128, 8, 32])
    k.dma('sp', Cr[:], Cre.rearrange("b p c -> p b c"), w=['Cr'])
    k.dma('sp', nCi[:], Cim.rearrange("b p c -> p b c"), w=['nCi'])
    k.ts('dve', nCi[:], nCi[:], -1.0, None, ALU.mult, None, ['nCi'], ['nCi'])
    car_r = k.sb("car_r", Cs)
    car_i = k.sb("car_i", Cs)
    k.memset('dve', car_r[:], 0.0, ['car_r0', 'car_r1'])
    k.memset('dve', car_i[:], 0.0, ['car_i0', 'car_i1'])
    ntriu = k.sb("ntriu", [128, 128])
    k.ts('dve', ntriu[:], triu[:], -1.0, None, ALU.mult, None, ['triu'], ['ntriu'])
    nCr = k.sb("nCr", [128, 8, 32])
    k.ts('dve', nCr[:], Cr[:], -1.0, None, ALU.mult, None, ['Cr'], ['nCr'])
    triur = k.sb("triur", [128, 128])
    k.cp('dve', triur[:], triu[:], ['triu'], ['triur'])
    Crr = k.sb("Crr", [128, 8, 32])
    k.cp('dve', Crr[:], Cr[:], ['Cr'], ['Crr'])
    nCir = k.sb("nCir", [128, 8, 32])
    k.cp('dve', nCir[:], nCi[:], ['nCi'], ['nCir'])
    k.pop_scope()
    if hasattr(k, 'rr_cache'):
        del k.rr_cache
    def ring(nm, shape, n, dt=F32):
        return [k.sb(f"{nm}{j}", shape, dt) for j in range(n)]
    FR_ = mybir.dt.float32r
    uTt = ring("uTt", [128, 128], 3)
    uTr = ring("uTr", [128, 128], 3, FR_)
    ut = ring("ut", [128, 128], 5)
    yo = ring("yo", [128, 128], 9)
    m1, m2, m3, m4 = ring("m1_", [128, 512], 3, FR_), ring("m2_", [128, 512], 3, FR_), ring("m3_", [128, 512], 3, FR_), ring("m4_", [128, 512], 3, FR_)
    Xtr, Xti = ring("Xtr", [128, 512], 3), ring("Xti", [128, 512], 3)
    Gr, Gi = ring("Gr", [128, 4, 128], 3), ring("Gi", [128, 4, 128], 3)
    n1, n2, n3, n4 = ring("n1_", [128, 512], 3, FR_), ring("n2_", [128, 512], 3, FR_), ring("n3_", [128, 512], 3, FR_), ring("n4_", [128, 512], 3, FR_)
    Hr, Hi = ring("Hr", [128, 4, 128], 3), ring("Hi", [128, 4, 128], 3)
    cc1 = [k.sb(f"cc1_{h}", [128, 4]) for h in range(2)]
    cc2 = [k.sb(f"cc2_{h}", [128, 4]) for h in range(2)]
    psXr = k.ps("psXr", [128, 512])
    psXi = k.ps("psXi", [128, 512])
    psGr = k.ps("psGr", [128, 512])
    psGi = k.ps("psGi", [128, 512])
    psY = k.ps("psY", [128, 512])
    fl = lambda t: t[:].rearrange("p b t -> p (b t)")

    def item(j):
        i, hc = divmod(j, 2)
        rows = slice(i * 128, (i + 1) * 128)
        cs_ = slice(hc * 512, (hc + 1) * 512)
        bs = slice(hc * 4, (hc + 1) * 4)
        def T(lst, nm):
            q = j % len(lst)
            return lst[q], f'{nm}{q}'
        uT_, kuT = T(uTt, 'uTt'); uR_, kuR = T(uTr, 'uTr'); ut_, kut = T(ut, 'ut'); yo_, kyo = T(yo, 'yo')
        m1_, km1 = T(m1, 'm1'); m2_, km2 = T(m2, 'm2'); m3_, km3 = T(m3, 'm3'); m4_, km4 = T(m4, 'm4')
        Xr_, kXr = T(Xtr, 'Xtr'); Xi_, kXi = T(Xti, 'Xti'); Gr_, kGr = T(Gr, 'Gr'); Gi_, kGi = T(Gi, 'Gi')
        n1_, kn1 = T(n1, 'n1'); n2_, kn2 = T(n2, 'n2'); n3_, kn3 = T(n3, 'n3'); n4_, kn4 = T(n4, 'n4')
        Hr_, kHr = T(Hr, 'Hr'); Hi_, kHi = T(Hi, 'Hi')
        k.dma('sp', uT_[:], uT[hc * 128:(hc + 1) * 128, rows], w=[kuT])
        k.dma('sp', ut_[:], u[rows, hc * 128:(hc + 1) * 128], w=[kut])
        yield
        k.cp('act', uR_[:], uT_[:], [kuT], [kuR])
        yield
        k.mm(psXr[:], uR_[:], BBr[:, hc, :], True, True, [kuR, 'BBr'], ['psXr'])
        k.mm(psXi[:], uR_[:], BBi[:, hc, :], True, True, [kuR, 'BBi'], ['psXi'])
        yield
        k.tt('dve', m1_[:], psXr[:], Pr[:, cs_], ALU.mult, ['psXr', 'Pr'], [km1])
        k.tt('dve', m3_[:], psXr[:], Pi[:, cs_], ALU.mult, ['psXr', 'Pi'], [km3])
        k.tt('dve', m2_[:], psXi[:], Pi[:, cs_], ALU.mult, ['psXi', 'Pi'], [km2])
        k.tt('dve', m4_[:], psXi[:], Pr[:, cs_], ALU.mult, ['psXi', 'Pr'], [km4])
        yield
        k.tt('pool', yo_[:], ut_[:], dbc[:, hc * 128:(hc + 1) * 128], ALU.mult, [kut, 'dbc'], [kyo])
        yield
        for nb in range(4):
            ns = slice(nb * 128, (nb + 1) * 128)
            k.mm(psGr[:, ns], m1_[:, ns], triur[:], True, False, [km1, 'triur'], ['psGr'])
            k.mm(psGr[:, ns], m2_[:, ns], ntriu[:], False, True, [km2, 'ntriu'], ['psGr'])
            k.mm(psGi[:, ns], m3_[:, ns], triur[:], True, False, [km3, 'triur'], ['psGi'])
            k.mm(psGi[:, ns], m4_[:, ns], triur[:], False, True, [km4, 'triur'], ['psGi'])
        yield
        k.tt('dve', Gr_[:], psGr[:].rearrange("p (b t) -> p b t", b=4),
             car_r[:, bs].unsqueeze(2).broadcast_to([128, 4, 128]), ALU.add, ['psGr', f'car_r{hc}'], [kGr])
        k.tt('dve', Gi_[:], psGi[:].rearrange("p (b t) -> p b t", b=4),
             car_i[:, bs].unsqueeze(2).broadcast_to([128, 4, 128]), ALU.add, ['psGi', f'car_i{hc}'], [kGi])
        gr127 = Gr_[:, :, 127]
        gi127 = Gi_[:, :, 127]
        CK = [f'cc1{hc}', f'cc2{hc}']
        k.tt('dve', cc1[hc][:], L128r[:, bs], gr127, ALU.mult, ['L128r', kGr], [CK[0]])
        k.tt('dve', cc2[hc][:], L128i[:, bs], gi127, ALU.mult, ['L128i', kGi], [CK[1]])
        k.tt('dve', car_r[:, bs], cc1[hc][:], cc2[hc][:], ALU.subtract, CK, [f'car_r{hc}'])
        k.tt('dve', cc1[hc][:], L128r[:, bs], gi127, ALU.mult, ['L128r', kGi], [CK[0]])
        k.tt('dve', cc2[hc][:], L128i[:, bs], gr127, ALU.mult, ['L128i', kGr], [CK[1]])
        k.tt('dve', car_i[:, bs], cc1[hc][:], cc2[hc][:], ALU.add, CK, [f'car_i{hc}'])
        yield
        qr = Qr[:, bs, :].rearrange("p b t -> p (b t)")
        qi = Qi[:, bs, :].rearrange("p b t -> p (b t)")
        k.tt('dve', n1_[:], fl(Gr_), qr, ALU.mult, [kGr, 'Qr'], [kn1])
        k.tt('dve', n2_[:], fl(Gi_), qi, ALU.mult, [kGi, 'Qi'], [kn2])
        k.tt('dve', n3_[:], fl(Gi_), qr, ALU.mult, [kGi, 'Qr'], [kn3])
        k.tt('dve', n4_[:], fl(Gr_), qi, ALU.mult, [kGr, 'Qi'], [kn4])
        yield
        for nb in range(4):
            blk = hc * 4 + nb
            ns = slice(nb * 128, (nb + 1) * 128)
            yo_s = psY[:, blk * 32:(blk + 1) * 32]
            k.mm(yo_s, n1_[:, ns], Crr[:, blk, :], True, False, [kn1, 'Crr'], ['psY'])
            k.mm(yo_s, n2_[:, ns], nCr[:, blk, :], False, False, [kn2, 'nCr'], ['psY'])
            k.mm(yo_s, n3_[:, ns], nCir[:, blk, :], False, False, [kn3, 'nCir'], ['psY'])
            k.mm(yo_s, n4_[:, ns], nCir[:, blk, :], False, True, [kn4, 'nCir'], ['psY'])
        yield
        k.tt('dve', yo_[:], yo_[:], psY[:, hc * 128:(hc + 1) * 128], ALU.add, [kyo, 'psY'], [kyo])
        yield
        k.dma('pool', y[rows, hc * 128:(hc + 1) * 128], yo_[:], r=[kyo], final=True)

    yield from pipeline_gen(item, 2 * NT)


def build_S5(L, k=None):
    k = k or K()
    for _ in gen_S5(L, k):
        pass
    return k.finish()


def s5_host_inputs(s, proj_u, prm):
    gs = slice(16 * s, 16 * s + 16)
    cs = slice(256 * s, 256 * s + 256)
    uc = np.ascontiguousarray(proj_u[:, cs])
    Bre = np.zeros((2, 128, 512), np.float32)
    Bim = np.zeros((2, 128, 512), np.float32)
    Cre = np.zeros((8, 128, 32), np.float32)
    Cim = np.zeros((8, 128, 32), np.float32)
    b_re, b_im = prm['s5_b_re'][gs], prm['s5_b_im'][gs]
    c_re, c_im = prm['s5_c_re'][gs], prm['s5_c_im'][gs]
    for g in range(16):
        hc, gl = g // 8, g % 8
        Bre[hc, gl * 16:(gl + 1) * 16, gl * 64:(gl + 1) * 64] = b_re[g].T
        Bim[hc, gl * 16:(gl + 1) * 16, gl * 64:(gl + 1) * 64] = b_im[g].T
        blk, g2 = g // 2, g % 2
        Cre[blk, g2 * 64:(g2 + 1) * 64, g2 * 16:(g2 + 1) * 16] = c_re[g].T
        Cim[blk, g2 * 64:(g2 + 1) * 64, g2 * 16:(g2 + 1) * 16] = c_im[g].T
    return dict(uT=np.ascontiguousarray(uc.T), u=uc,
                lam_re=np.ascontiguousarray(prm['s5_lambda_re'][gs].reshape(-1)),
                lam_im=np.ascontiguousarray(prm['s5_lambda_im'][gs].reshape(-1)),
                lstep=np.ascontiguousarray(np.repeat(prm['s5_log_step'][gs], 64)),
                Bre=Bre, Bim=Bim, Cre=Cre, Cim=Cim, dsk=np.ascontiguousarray(prm['s5_d'][cs]),
                triu=np.triu(np.ones((128, 128), np.float32)),
                iota_p=np.arange(128, dtype=np.float32).reshape(128, 1),
                iota_f=np.tile(np.arange(128, dtype=np.float32)[None], (128, 1)))


GELU_C = 1.5957691216057308


def gen_LRU(L, k):
    TT = 512
    NCH = L // TT
    xbT = k.din("xbT", [256, L])
    gateT = k.din("gateT", [256, L])
    cw_d = k.din("cw", [128, 2, 4])
    cb_d = k.din("cb", [128, 2])
    Wa_d = k.din("Wa", [2, 128, 128])
    Wx_d = k.din("Wx", [2, 128, 128])
    ba_d = k.din("ba", [128, 2])
    bx_d = k.din("bx", [128, 2])
    lam_d = k.din("lam", [128, 2])
    odT = k.dout("odT", [256, L])
    cw = k.sb("cw_s", [128, 2, 4])
    cb = k.sb("cb_s", [128, 2])
    Wa = k.sb("Wa_s", [128, 2, 128])
    Wx = k.sb("Wx_s", [128, 2, 128])
    ba = k.sb("ba_s", [128, 2])
    bx = k.sb("bx_s", [128, 2])
    c8 = k.sb("c8", [128, 2])
    k.dma('sp', cw[:], cw_d, w=['cw'])
    k.dma('sp', cb[:], cb_d, w=['cb'])
    k.dma('sp', Wa[:], Wa_d.rearrange("b p n -> p b n"), w=['Wa'])
    k.dma('sp', Wx[:], Wx_d.rearrange("b p n -> p b n"), w=['Wx'])
    k.dma('sp', ba[:], ba_d, w=['ba'])
    k.dma('sp', bx[:], bx_d, w=['bx'])
    k.dma('sp', c8[:], lam_d, w=['c8'])
    k.act(c8[:], c8[:], AF.Exp, ['c8'], ['c8'], scale=-1.0)
    k.act(c8[:], c8[:], AF.Ln, ['c8'], ['c8'], bias=1.0)
    k.ts('dve', c8[:], c8[:], -8.0, None, ALU.mult, None, ['c8'], ['c8'])
    hlast = k.sb("hlast", [128, 2])
    k.memset('dve', hlast[:], 0.0, ['hlast0', 'hlast1'])

    def ring(nm, shape, n):
        return [k.sb(f"{nm}{j}", shape) for j in range(n)]
    xh = ring("xh", [128, TT + 3], 3)
    gt = ring("gt", [128, TT], 8)
    xc = ring("xc", [128, TT], 5)
    r, ig, a, a2 = ring("r", [128, TT], 2), ring("ig", [128, TT], 3), ring("a", [128, TT], 5), ring("a2", [128, TT], 3)
    bt = ring("bt", [128, TT], 4)
    g2 = ring("g2", [128, TT], 5)
    h = ring("h", [128, TT], 2)
    ot = ring("ot", [128, TT], 3)
    psR = k.ps("psR", [128, TT])
    psI = k.ps("psI", [128, TT])

    def item(n):
        c, pb = divmod(n, 2)
        prow = slice(pb * 128, (pb + 1) * 128)
        def T(lst, nm):
            j = n % len(lst)
            return lst[j], f'{nm}{j}'
        xh_, kxh = T(xh, 'xh'); gt_, kgt = T(gt, 'gt'); xc_, kxc = T(xc, 'xc'); r_, kr = T(r, 'r'); ig_, kig = T(ig, 'ig')
        a_, ka = T(a, 'a'); a2_, ka2 = T(a2, 'a2'); bt_, kbt = T(bt, 'bt'); g2_, kg2 = T(g2, 'g2'); h_, kh = T(h, 'h'); ot_, kot = T(ot, 'ot')
        if c == 0:
            k.memset('dve', xh_[:, 0:3], 0.0, [kxh + 'h'])
            k.dma('sp', xh_[:, 3:TT + 3], xbT[prow, 0:TT], w=[kxh])
        else:
            k.dma('sp', xh_[:, 0:TT + 3], xbT[prow, c * TT - 3:(c + 1) * TT], w=[kxh, kxh + 'h'])
        k.dma('sp', gt_[:], gateT[prow, c * TT:(c + 1) * TT], w=[kgt])
        yield
        xk = [kxh, kxh + 'h']
        k.ts('dve', xc_[:], xh_[:, 3:TT + 3], cw[:, pb, 3:4], cb[:, pb:pb + 1], ALU.mult, ALU.add, xk + ['cw', 'cb'], [kxc])
        for j in (2, 1, 0):
            k.stt(xc_[:], xh_[:, j:j + TT], cw[:, pb, j:j + 1], xc_[:], ALU.mult, ALU.add, xk + ['cw', kxc], [kxc])
        yield
        k.mm(psR[:], Wa[:, pb, :], xc_[:], True, True, ['Wa', kxc], ['psR'])
        k.mm(psI[:], Wx[:, pb, :], xc_[:], True, True, ['Wx', kxc], ['psI'])
        yield
        k.act(r_[:], psR[:], AF.Sigmoid, ['psR', 'ba'], [kr], bias=ba[:, pb:pb + 1])
        k.act(ig_[:], psI[:], AF.Sigmoid, ['psI', 'bx'], [kig], bias=bx[:, pb:pb + 1])
        k.act(a_[:], r_[:], AF.Exp, [kr, 'c8'], [ka], scale=c8[:, pb:pb + 1])
        k.act(a2_[:], a_[:], AF.Square, [ka], [ka2])
        k.act(a2_[:], a2_[:], AF.Sqrt, [ka2], [ka2], scale=-1.0, bias=1.0)
        k.act(g2_[:], gt_[:], AF.Square, [kgt], [kg2])
        k.act(g2_[:], g2_[:], AF.Copy, [kg2], [kg2], scale=0.044715, bias=1.0)
        yield
        k.tt('dve', bt_[:], ig_[:], xc_[:], ALU.mult, [kig, kxc], [kbt])
        k.tt('dve', bt_[:], bt_[:], a2_[:], ALU.mult, [kbt, ka2], [kbt])
        k.tt('dve', g2_[:], g2_[:], gt_[:], ALU.mult, [kg2, kgt], [kg2])
        yield
        k.act(g2_[:], g2_[:], AF.Sigmoid, [kg2], [kg2], scale=GELU_C)
        yield
        k.P.op('dve', lambda e: e.tensor_tensor_scan(out=h_[:], data0=a_[:], data1=bt_[:], initial=hlast[:, pb:pb + 1],
                                                     op0=ALU.mult, op1=ALU.add),
               reads=[ka, kbt, f'hlast{pb}'], writes=[kh])
        k.cp('dve', hlast[:, pb:pb + 1], h_[:, TT - 1:TT], [kh], [f'hlast{pb}'])
        k.tt('dve', g2_[:], g2_[:], gt_[:], ALU.mult, [kg2, kgt], [kg2])
        k.tt('dve', ot_[:], h_[:], g2_[:], ALU.mult, [kh, kg2], [kot])
        yield
        k.dma('pool', odT[prow, c * TT:(c + 1) * TT], ot_[:], r=[kot], final=True)

    yield from pipeline_gen(item, 2 * NCH)


def build_LRU(L, k=None):
    k = k or K()
    for _ in gen_LRU(L, k):
        pass
    return k.finish()


def lru_host_inputs(s, xb, gate, prm):
    cs = slice(256 * s, 256 * s + 256)
    col = lambda v: np.ascontiguousarray(v[cs].reshape(2, 128).T)
    Wa = np.zeros((2, 128, 128), np.float32)
    Wx = np.zeros((2, 128, 128), np.float32)
    for pb in range(2):
        for bl in range(2):
            blk = 4 * s + 2 * pb + bl
            Wa[pb, bl * 64:(bl + 1) * 64, bl * 64:(bl + 1) * 64] = prm['lru_w_a'][blk]
            Wx[pb, bl * 64:(bl + 1) * 64, bl * 64:(bl + 1) * 64] = prm['lru_w_x'][blk]
    cw = np.ascontiguousarray(prm['lru_conv_w'][:, cs].reshape(4, 2, 128).transpose(2, 1, 0))
    return dict(xbT=np.ascontiguousarray(xb[:, cs].T), gateT=np.ascontiguousarray(gate[:, cs].T), cw=cw,
                cb=col(prm['lru_conv_b']), Wa=Wa, Wx=Wx, ba=col(prm['lru_b_a']), bx=col(prm['lru_b_x']),
                lam=col(prm['lru_lambda']))


GN_EPS = 64e-5
NLEV = 5


def build_RWKV(L, k=None, NH=4, fr=False, CH=64):
    k = k or K()
    NT = L // 128
    W = NH * 64
    NG = NH // 4
    FR = mybir.dt.float32r if fr else F32
    rd = (lambda ap: ap.bitcast(F32)) if fr else (lambda ap: ap)
    NCK = 128 // CH
    nlev = 5 if CH == 64 else 6
    frc = fr and CH == 128
    FRC = mybir.dt.float32r if frc else F32
    rdc = (lambda ap: ap.bitcast(F32)) if frc else (lambda ap: ap)
    lhc = (lambda ap: ap) if frc else rd
    prkv = [k.din(nm, [L, W]) for nm in ("pr", "pk", "pv")]
    mu1 = k.din("mu1", [3 * W])
    pls = [k.din("plw", [64, L]), k.din("pla", [64, L]), k.din("plg", [128, L])]
    mul = k.din("mul", [128, 3])
    w2 = k.din("w2", [64, W])
    a2 = k.din("a2", [64, W])
    g2 = k.din("g2", [128, W])
    vecs = k.din("vecs", [7, W])
    ident_d = k.din("ident", [128, 128])
    triw_d = k.din("triw", [3, 128, 128])
    mask5_d = k.din("mask5", [128, 640])
    rowm_d = k.din("rowm", [128, 2])
    oc = k.dout("oc", [L, W])

    k.consts(ident_d)
    triw = k.sb("triw_s", [128, 3, 128])
    k.dma('sp', triw[:], triw_d.rearrange("a p n -> p a n"), w=['triw'])
    mask5 = k.sb("mask5_s", [128, 640])
    k.dma('sp', mask5[:], mask5_d, w=['mask5'])
    rowm = k.sb("rowm_s", [128, 2])
    k.dma('sp', rowm[:], rowm_d, w=['rowm'])
    mu1bc = k.bcast_row("mu1bc", mu1, 3 * W)
    vb = [k.bcast_row(f"vb{i}", vecs[i], W) for i in range(7)]
    w0bc, a0bc, kkbc, kabc, rkbc, lngbc, lnbbc = vb
    VK = [f"vb{i}" for i in range(7)]
    muls = k.sb("muls", [128, 3])
    k.dma('sp', muls[:], mul, w=['muls'])
    w2s = k.sb("w2s", [64, W])
    a2s = k.sb("a2s", [64, W])
    k.dma('sp', w2s[:], w2, w=['w2s'])
    k.dma('sp', a2s[:], a2, w=['a2s'])
    g2s = k.sb("g2s", [128, W])
    k.dma('sp', g2s[:], g2, w=['g2s'])
    ST = [k.sb(f"ST{i}", [64, 64], FRC) for i in range(NH)]
    zt = k.sb("zt", [128, W])
    k.memset('dve', zt[:], 0.0, ['zt'])
    for i in range(NH):
        k.cp('dve', ST[i][:], zt[0:64, 0:64], ['zt'], [f'ST{i}'])
    P1s = k.sb("P1s", [128, W], FRC)
    Us = k.sb("Us", [128, W], FRC)
    k.cp('dve', P1s[:], zt[:], ['zt'], ['P1s'])
    k.cp('dve', Us[:], zt[:], ['zt'], ['Us'])

    pt = [k.sb(f"pt{i}", [128, 3 * W]) for i in range(2)]
    pp = [k.sb(f"pp{i}", [128, 3 * W]) for i in range(2)]
    lt = [k.sb(f"lt{i}", [128, 3, 128]) for i in range(2)]
    lp = [k.sb(f"lp{i}", [128, 3, 128]) for i in range(2)]
    for i_ in range(2):
        k.memset('pool', lt[i_][:], 0.0, [f'lt{i_}0', f'lt{i_}1', f'lt{i_}2'])
        k.memset('pool', lp[i_][:], 0.0, [f'lp{i_}0', f'lp{i_}1', f'lp{i_}2', f'lp{i_}z'])
    pm = k.sb("pm", [128, 3 * W])
    vr = k.sb("vr", [128, W], FR)
    lm = k.sb("lm", [128, 3, 128])
    sw = k.sb("sw", [128, W])
    av = k.sb("av", [128, W])
    gv = k.sb("gv", [128, W])
    kkr = k.sb("kkr", [128, W])
    sq = k.sb("sq", [128, W])
    s4 = k.sb("s4", [128, NH])
    rn = k.sb("rn", [128, NH])
    nkk = k.sb("nkk", [128, W])
    kmod = k.sb("kmod", [128, W])
    kka = k.sb("kka", [128, W])
    tmp = k.sb("tmp", [128, W])
    bon = k.sb("bon", [128, NH])
    E1 = k.sb("E1", [128, W])
    E2 = k.sb("E2", [128, W])
    E3 = k.sb("E3", [128, W])
    E4 = k.sb("E4", [128, W])
    E1T = k.sb("E1T", [64, NH, 128])
    At = k.sb("At", [128, W])
    Bs = k.sb("Bs", [128, W])
    Ks = k.sb("Ks", [128, W])
    Rt = k.sb("Rt", [128, W])
    Bfm = [k.sb(f"Bfm{c}", [128, W]) for c in range(2)]
    Kfm = [k.sb(f"Kfm{c}", [128, W]) for c in range(2)]
    FT = [k.sb(f"FT{h}", [64, 4, 128], FR) for h in range(NH)]
    A5 = [k.sb(f"A5_{h}", [128, 640], FR) for h in range(NH)]
    NL = [k.sb(f"NL_{h}", [128, 256], FR) for h in range(NH)]
    PQ = [k.sb(f"PQ_{h}", [128, 256], FR) for h in range(NH)]
    W1 = k.sb("W1", [128, W], FR)
    U1 = k.sb("U1", [128, W])
    ysb = k.sb("ysb", [128, W])
    yc = k.sb("yc", [128, W])
    m4 = k.sb("m4", [128, NH])
    r4 = k.sb("r4", [128, NH])
    ot = [k.sb(f"ot{i}", [128, W]) for i in range(2)]
    B = [k.ps(f"psB{i}", [128, 512]) for i in range(8)]
    bk = lambda i: f'psB{i}'
    v3 = lambda t: t.rearrange("p (h j) -> p h j", h=NH)
    bc4 = lambda t: t.unsqueeze(2).broadcast_to([128, NH, 64])

    for i in range(NT):
        b = i % 2
        rows = slice(i * 128, (i + 1) * 128)
        PK, PPK, LTK, LPK = [], [], [], []
        for q in range(3):
            cq = slice(q * W, (q + 1) * W)
            k.dma('sp', pt[b][:, cq], prkv[q][rows, :], w=[f'pt{b}{q}'])
            PK.append(f'pt{b}{q}')
            if i == 0:
                k.dma('sp', pp[b][1:128, cq], prkv[q][0:127, :], w=[f'pp{b}{q}'])
            else:
                k.dma('sp', pp[b][:, cq], prkv[q][i * 128 - 1:i * 128 + 127, :], w=[f'pp{b}{q}'])
            PPK.append(f'pp{b}{q}')
            nr = pls[q].shape[0]
            k.dma('sp', lt[b][0:nr, q, :], pls[q][:, rows], w=[f'lt{b}{q}'])
            LTK.append(f'lt{b}{q}')
            if i == 0:
                k.dma('sp', lp[b][0:nr, q, 1:128], pls[q][:, 0:127], w=[f'lp{b}{q}'])
            else:
                k.dma('sp', lp[b][0:nr, q, :], pls[q][:, i * 128 - 1:i * 128 + 127], w=[f'lp{b}{q}'])
            LPK.append(f'lp{b}{q}')
        if i == 0:
            k.memset('pool', pp[b][0:1, :], 0.0, [f'pp{b}z'])
            k.memset('pool', lp[b][:, :, 0:1], 0.0, [f'lp{b}z'])
            PPK.append(f'pp{b}z')
            LPK.append(f'lp{b}z')
        k.tt('pool', pm[:], pp[b][:], pt[b][:], ALU.subtract, PPK + PK, ['pm'])
        k.tt('pool', pm[:], pm[:], mu1bc[:], ALU.mult, ['pm', 'mu1bc'], ['pm'])
        k.tt('pool', pm[:], pm[:], pt[b][:], ALU.add, ['pm'] + PK, ['pm'])
        r_, k_, v_ = pm[:, 0:W], pm[:, W:2 * W], pm[:, 2 * W:3 * W]
        k.cp('act', vr[:], v_, ['pm'], ['vr'])
        LK = LTK + LPK
        k.tt('dve', lm[:], lp[b][:], lt[b][:], ALU.subtract, LK, ['lm'])
        for blk in range(3):
            k.stt(lm[:, blk, :], lm[:, blk, :], muls[:, blk:blk + 1], lt[b][:, blk, :], ALU.mult, ALU.add,
                  ['lm', 'muls'] + LK, ['lm'])
        k.act(lm[0:64, 0, :], lm[0:64, 0, :], AF.Tanh, ['lm'], ['lm'])
        k.act(lm[:, 2, :], lm[:, 2, :], AF.Sigmoid, ['lm'], ['lm'])
        k.mm(B[0][:, 0:W], lm[0:64, 0, :], w2s[:], True, True, ['lm', 'w2s'], [bk(0)])
        k.mm(B[1][:, 0:W], lm[0:64, 1, :], a2s[:], True, True, ['lm', 'a2s'], [bk(1)])
        k.mm(B[2][:, 0:W], lm[:, 2, :], g2s[:], True, True, ['lm', 'g2s'], [bk(2)])
        k.tt('dve', sw[:], B[0][:, 0:W], w0bc[:], ALU.add, [bk(0), VK[0]], ['sw'])
        k.act(sw[:], sw[:], AF.Sigmoid, ['sw'], ['sw'])
        k.tt('dve', av[:], B[1][:, 0:W], a0bc[:], ALU.add, [bk(1), VK[1]], ['av'])
        k.act(av[:], av[:], AF.Sigmoid, ['av'], ['av'])
        k.cp('act', gv[:], B[2][:, 0:W], [bk(2)], ['gv'])
        k.tt('pool', kkr[:], k_, kkbc[:], ALU.mult, ['pm', VK[2]], ['kkr'])
        k.tt('pool', sq[:], kkr[:], kkr[:], ALU.mult, ['kkr'], ['sq'])
        k.P.op('dve', lambda e: e.tensor_reduce(out=s4[:], in_=v3(sq[:]), axis=AX.X, op=ALU.add), reads=['sq'], writes=['s4'])
        k.act(s4[:], s4[:], AF.Sqrt, ['s4'], ['s4'])
        k.ts('dve', s4[:], s4[:], 1e-12, None, ALU.max, None, ['s4'], ['s4'])
        k.recip(rn[:], s4[:], ['s4'], ['rn'])
        k.ts('dve', rn[:], rn[:], -1.0, None, ALU.mult, None, ['rn'], ['rn'])
        k.tt('dve', v3(nkk[:]), v3(kkr[:]), bc4(rn[:]), ALU.mult, ['kkr', 'rn'], ['nkk'])
        k.stt(tmp[:], av[:], -1.0, kabc[:], ALU.add, ALU.mult, ['av', VK[3]], ['tmp'])
        k.stt(kmod[:], tmp[:], 1.0, k_, ALU.add, ALU.mult, ['tmp', 'pm'], ['kmod'])
        k.stt(kka[:], nkk[:], -1.0, av[:], ALU.mult, ALU.mult, ['nkk', 'av'], ['kka'])
        k.tt('pool', tmp[:], r_, kmod[:], ALU.mult, ['pm', 'kmod', 'tmp'], ['tmp'])
        k.tt('pool', tmp[:], tmp[:], rkbc[:], ALU.mult, ['tmp', VK[4]], ['tmp'])
        k.P.op('dve', lambda e: e.tensor_reduce(out=bon[:], in_=v3(tmp[:]), axis=AX.X, op=ALU.add), reads=['tmp'], writes=['bon'])
        k.mm(B[3][:, 0:W], triw[:, 0, :], sw[:], True, True, ['triw', 'sw'], [bk(3)])
        k.mm(B[4][:, 0:W], triw[:, 1, :], sw[:], True, True, ['triw', 'sw'], [bk(4)])
        k.mm(B[5][:, 0:W], triw[:, 2, :], sw[:], True, True, ['triw', 'sw'], [bk(5)])
        for h in range(NH):
            k.mm(B[6 + h // 4][0:64, (h % 4) * 128:(h % 4 + 1) * 128], sw[:, h * 64:(h + 1) * 64], triw[:, 0, :], True, True,
                 ['sw', 'triw'], [bk(6 + h // 4)])
        k.act(E1[:], B[3][:, 0:W], AF.Exp, [bk(3)], ['E1'])
        k.act(E2[:], B[3][:, 0:W], AF.Exp, [bk(3)], ['E2'], scale=-1.0)
        k.act(E3[:], B[4][:, 0:W], AF.Exp, [bk(4)], ['E3'])
        k.act(E4[:], B[5][:, 0:W], AF.Exp, [bk(5)], ['E4'])
        for g in range(NG):
            k.act(E1T[:, 4 * g:4 * g + 4, :].rearrange("p a t -> p (a t)"), B[6 + g][0:64, :], AF.Exp, [bk(6 + g)], ['E1T'])
        k.tt('dve', At[:], nkk[:], E3[:], ALU.mult, ['nkk', 'E3'], ['At'])
        k.tt('pool', Bs[:], kka[:], E2[:], ALU.mult, ['kka', 'E2'], ['Bs'])
        k.tt('dve', Ks[:], kmod[:], E2[:], ALU.mult, ['kmod', 'E2'], ['Ks'])
        k.tt('pool', Rt[:], r_, E1[:], ALU.mult, ['pm', 'E1'], ['Rt'])
        for c in range(NCK):
            k.stt(Bfm[c][:], kka[:], rowm[:, c:c + 1], E4[:], ALU.mult, ALU.mult, ['kka', 'E4', 'rowm'], [f'Bfm{c}'])
            k.stt(Kfm[c][:], kmod[:], rowm[:, c:c + 1], E4[:], ALU.mult, ALU.mult, ['kmod', 'E4', 'rowm'], [f'Kfm{c}'])
        HS = list(range(NH))
        for h in HS:
            cs_ = slice(h * 64, (h + 1) * 64)
            for q, (src, key) in enumerate([(At, 'At'), (Bs, 'Bs'), (Ks, 'Ks'), (Rt, 'Rt')]):
                k.tr(B[h][0:64, q * 128:(q + 1) * 128], src[:, cs_], k.identf[:], [key], [bk(h)])
        for h in HS:
            k.cp('act' if h % 2 else 'dve', FT[h][:].rearrange("p a t -> p (a t)"), B[h][0:64, :], [bk(h)], [f'FT{h}'])
        for h in HS:
            AtT, BsT, KsT, RtT = (FT[h][:, q, :] for q in range(4))
            o = lambda j: B[h][:, j * 128:(j + 1) * 128]
            k.mm(o(0), BsT, AtT, True, True, [f'FT{h}'], [bk(h)])
            k.mm(o(1), AtT, BsT, True, True, [f'FT{h}'], [bk(h)])
            k.mm(o(2), KsT, AtT, True, True, [f'FT{h}'], [bk(h)])
        for h in HS:
            k.tt('dve', A5[h][:, 0:384], B[h][:, 0:384], mask5[:, 0:384], ALU.mult, [bk(h), 'mask5'], [f'A5_{h}'])
        for h in HS:
            AtT, BsT, KsT, RtT = (FT[h][:, q, :] for q in range(4))
            k.mm(B[h][:, 0:128], BsT, RtT, True, True, [f'FT{h}'], [bk(h)])
            k.mm(B[h][:, 128:256], KsT, RtT, True, True, [f'FT{h}'], [bk(h)])
        for h in HS:
            k.tt('dve', A5[h][:, 384:640], B[h][:, 0:256], mask5[:, 384:640], ALU.mult, [bk(h), 'mask5'], [f'A5b_{h}'])
            k.cp('act', NL[h][:], rd(A5[h][:, 0:256]), [f'A5_{h}'], [f'NL_{h}'])
            k.tt('pool' if not fr else 'dve', PQ[h][:].rearrange("p (a n) -> p a n", a=2), rd(A5[h][:, 0:256]).rearrange("p (a n) -> p a n", a=2),
                 k.identf[:].unsqueeze(1).broadcast_to([128, 2, 128]), ALU.add, [f'A5_{h}', 'ident'], [f'PQ_{h}'])
        for lev in range(nlev):
            for h in HS:
                N_, L_ = NL[h][:, 0:128], NL[h][:, 128:256]
                k.mm(B[h][:, 0:128], L_, N_, True, True, [f'NL_{h}'], [bk(h)])
                k.mm(B[h][:, 128:256], N_, L_, True, True, [f'NL_{h}'], [bk(h)])
            for h in HS:
                k.cp('act', NL[h][:], B[h][:, 0:256], [bk(h)], [f'NL_{h}'])
            for h in HS:
                N_, L_ = NL[h][:, 0:128], NL[h][:, 128:256]
                P_, Q_ = PQ[h][:, 0:128], PQ[h][:, 128:256]
                k.mm(B[h][:, 256:384], Q_, N_, True, True, [f'NL_{h}', f'PQ_{h}'], [bk(h)])
                k.mm(B[h][:, 384:512], P_, L_, True, True, [f'NL_{h}', f'PQ_{h}'], [bk(h)])
            for h in HS:
                k.tt('dve', PQ[h][:], B[h][:, 256:512], rd(PQ[h][:]), ALU.add, [bk(h), f'PQ_{h}'], [f'PQ_{h}'])
        for h in range(NH):
            k.mm(B[0][:, h * 64:(h + 1) * 64], A5[h][:, 256:384], vr[:, h * 64:(h + 1) * 64], True, True, [f'A5_{h}', 'vr'], [bk(0)])
        k.cp('act', W1[:], B[0][:, 0:W], [bk(0)], ['W1'])
        for h in range(NH):
            k.mm(B[1][:, h * 64:(h + 1) * 64], PQ[h][:, 0:128], W1[:, h * 64:(h + 1) * 64], True, True,
                 [f'PQ_{h}', 'W1'], [bk(1)])
        k.cp('act', U1[:], B[1][:, 0:W], [bk(1)], ['U1'])
        vsrc = vr if frc else None
        for c in range(NCK):
            cr = slice(c * CH, (c + 1) * CH)
            for h in range(NH):
                k.mm(B[2][cr, h * 64:(h + 1) * 64], lhc(FT[h][:, 0, cr]), ST[h][:], True, True, [f'FT{h}', f'ST{h}'], [bk(2)])
            k.cp('act', P1s[cr, :], B[2][cr, 0:W], [bk(2)], ['P1s'])
            for h in range(NH):
                k.mm(B[3][cr, h * 64:(h + 1) * 64], lhc(PQ[h][:, cr]), P1s[:, h * 64:(h + 1) * 64], True, True,
                     [f'PQ_{h}', 'P1s'], [bk(3)])
            k.tt('dve', Us[cr, :], B[3][cr, 0:W], U1[cr, :], ALU.add, [bk(3), 'U1'], ['Us'])
            for h in range(NH):
                hc_ = slice(h * 64, (h + 1) * 64)
                vh = vr[:, hc_] if frc else pm[:, 2 * W + h * 64:2 * W + (h + 1) * 64]
                vk = 'vr' if frc else 'pm'
                k.mm(B[6][cr, hc_], lhc(FT[h][:, 3, cr]), ST[h][:], True, False, [f'FT{h}', f'ST{h}'], [bk(6)])
                k.mm(B[6][cr, hc_], lhc(A5[h][:, 384:512][:, cr]), Us[:, hc_], False, False, [f'A5b_{h}', 'Us'], [bk(6)])
                k.mm(B[6][cr, hc_], lhc(A5[h][:, 512:640][:, cr]), vh, False, True, [f'A5b_{h}', vk], [bk(6)])
            for h in range(NH):
                hc_ = slice(h * 64, (h + 1) * 64)
                vh = pm[:, 2 * W + h * 64:2 * W + (h + 1) * 64]
                k.mm(B[7][0:64, hc_], Bfm[c][:, hc_], rdc(Us[:, hc_]), True, False, [f'Bfm{c}', 'Us'], [bk(7)])
                k.mm(B[7][0:64, hc_], Kfm[c][:, hc_], vh, False, True, [f'Kfm{c}', 'pm'], [bk(7)])
            for h in range(NH):
                hc_ = slice(h * 64, (h + 1) * 64)
                k.stt(ST[h][:], rdc(ST[h][:]), E1T[:, h, (c + 1) * CH - 1:(c + 1) * CH], B[7][0:64, hc_], ALU.mult, ALU.add,
                      [f'ST{h}', 'E1T', bk(7)], [f'ST{h}'])
        k.cp('act', ysb[:], B[6][:, 0:W], [bk(6)], ['ysb'])
        k.P.op('dve', lambda e: e.tensor_reduce(out=m4[:], in_=v3(ysb[:]), axis=AX.X, op=ALU.add), reads=['ysb'], writes=['m4'])
        k.ts('dve', m4[:], m4[:], -1.0 / 64.0, None, ALU.mult, None, ['m4'], ['m4'])
        k.tt('dve', v3(yc[:]), v3(ysb[:]), bc4(m4[:]), ALU.add, ['ysb', 'm4'], ['yc'])
        k.tt('pool', sq[:], yc[:], yc[:], ALU.mult, ['yc'], ['sq'])
        k.P.op('dve', lambda e: e.tensor_reduce(out=r4[:], in_=v3(sq[:]), axis=AX.X, op=ALU.add), reads=['sq'], writes=['r4'])
        k.ts('dve', r4[:], r4[:], 1.0 / 64.0, GN_EPS, ALU.mult, ALU.add, ['r4'], ['r4'])
        k.act(r4[:], r4[:], AF.Sqrt, ['r4'], ['r4'])
        k.recip(r4[:], r4[:], ['r4'], ['r4'])
        k.tt('dve', v3(yc[:]), v3(yc[:]), bc4(r4[:]), ALU.mult, ['yc', 'r4'], ['yc'])
        k.tt('pool', yc[:], yc[:], lngbc[:], ALU.mult, ['yc', VK[5]], ['yc'])
        k.tt('pool', yc[:], yc[:], lnbbc[:], ALU.add, ['yc', VK[6]], ['yc'])
        k.tt('dve', v3(tmp[:]), v3(v_), bc4(bon[:]), ALU.mult, ['pm', 'bon', 'tmp'], ['tmp'])
        k.tt('pool', yc[:], yc[:], tmp[:], ALU.add, ['yc', 'tmp'], ['yc'])
        k.tt('dve', ot[b][:], yc[:], gv[:], ALU.mult, ['yc', 'gv'], [f'ot{b}'])
        k.dma('pool', oc[rows, :], ot[b][:], r=[f'ot{b}'], final=True)
    return k.finish()


def build_RWKVP(L, k=None, CH=64):
    NH, fr = 8, True
    k = k or K()
    NT = L // 128
    W = NH * 64
    NG = NH // 4
    FR = mybir.dt.float32r if fr else F32
    rd = (lambda ap: ap.bitcast(F32)) if fr else (lambda ap: ap)
    NCK = 128 // CH
    nlev = 5 if CH == 64 else 6
    frc = True
    FRC = mybir.dt.float32r if frc else F32
    rdc = (lambda ap: ap.bitcast(F32)) if frc else (lambda ap: ap)
    lhc = (lambda ap: ap) if frc else rd
    prkv = [k.din(nm, [L, W]) for nm in ("pr", "pk", "pv")]
    mu1 = k.din("mu1", [3 * W])
    pls = [k.din("plw", [64, L]), k.din("pla", [64, L]), k.din("plg", [128, L])]
    mul = k.din("mul", [128, 3])
    w2 = k.din("w2", [64, W])
    a2 = k.din("a2", [64, W])
    g2 = k.din("g2", [128, W])
    vecs = k.din("vecs", [7, W])
    ident_d = k.din("ident", [128, 128])
    triw_d = k.din("triw", [3, 128, 128])
    mask5_d = k.din("mask5", [128, 640])
    rowm_d = k.din("rowm", [128, 2])
    oc = k.dout("oc", [L, W])

    k.consts(ident_d)
    triw = k.sb("triw_s", [128, 3, 128])
    k.dma('sp', triw[:], triw_d.rearrange("a p n -> p a n"), w=['triw'])
    mask5 = k.sb("mask5_s", [128, 640])
    k.dma('sp', mask5[:], mask5_d, w=['mask5'])
    rowm = k.sb("rowm_s", [128, 2])
    k.dma('sp', rowm[:], rowm_d, w=['rowm'])
    mu1bc = k.bcast_row("mu1bc", mu1, 3 * W)
    vb = [k.bcast_row(f"vb{i}", vecs[i], W) for i in range(7)]
    w0bc, a0bc, kkbc, kabc, rkbc, lngbc, lnbbc = vb
    VK = [f"vb{i}" for i in range(7)]
    muls = k.sb("muls", [128, 3])
    k.dma('sp', muls[:], mul, w=['muls'])
    w2s = k.sb("w2s", [64, W])
    a2s = k.sb("a2s", [64, W])
    k.dma('sp', w2s[:], w2, w=['w2s'])
    k.dma('sp', a2s[:], a2, w=['a2s'])
    g2s = k.sb("g2s", [128, W])
    k.dma('sp', g2s[:], g2, w=['g2s'])
    ST = [k.sb(f"ST{i}", [64, 64], FRC) for i in range(NH)]
    zt = k.sb("zt", [128, W])
    k.memset('dve', zt[:], 0.0, ['zt'])
    for i in range(NH):
        k.cp('dve', ST[i][:], zt[0:64, 0:64], ['zt'], [f'ST{i}'])
    P1s = k.sb("P1s", [128, W], FRC)
    Us = k.sb("Us", [128, W], FRC)
    k.cp('dve', P1s[:], zt[:], ['zt'], ['P1s'])
    k.cp('dve', Us[:], zt[:], ['zt'], ['Us'])

    pt = [k.sb("pt0", [128, 3 * W])] * 2
    pp = [k.sb("pp0", [128, 3 * W])] * 2
    lt = [k.sb("lt0", [128, 3, 128])] * 2
    lp = [k.sb("lp0", [128, 3, 128])] * 2
    k.memset('pool', lt[0][:], 0.0, ['lt0', 'lt1', 'lt2'])
    k.memset('pool', lp[0][:], 0.0, ['lp0', 'lp1', 'lp2', 'lpz'])
    pm2 = [k.sb(f"pm{i_}", [128, 3 * W]) for i_ in range(2)]
    vr2 = [k.sb(f"vr{i_}", [128, W], FR) for i_ in range(2)]
    lm2 = [k.sb(f"lm{i_}", [128, 3, 128]) for i_ in range(2)]
    sw = k.sb("sw", [128, W])
    av = k.sb("av", [128, W])
    gv2 = [k.sb(f"gv{i_}", [128, W]) for i_ in range(2)]
    kkr = k.sb("kkr", [128, W])
    sq = k.sb("sq", [128, W])
    s4 = k.sb("s4", [128, NH])
    rn = k.sb("rn", [128, NH])
    nkk = k.sb("nkk", [128, W])
    kmod = k.sb("kmod", [128, W])
    kka = k.sb("kka", [128, W])
    tmp = k.sb("tmp", [128, W])
    bon2 = [k.sb(f"bon{i_}", [128, NH]) for i_ in range(2)]
    E1 = k.sb("E1", [128, W])
    E2 = k.sb("E2", [128, W])
    E3 = k.sb("E3", [128, W])
    E4 = k.sb("E4", [128, W])
    E1T2 = [k.sb(f"E1T{i_}", [64, NH, 128]) for i_ in range(2)]
    At2 = [k.sb(f"At{i_}", [128, W]) for i_ in range(2)]
    Bs2 = [k.sb(f"Bs{i_}", [128, W]) for i_ in range(2)]
    Ks2 = [k.sb(f"Ks{i_}", [128, W]) for i_ in range(2)]
    Rt2 = [k.sb(f"Rt{i_}", [128, W]) for i_ in range(2)]
    Bfm2 = [[k.sb(f"Bfm{p_}{c}", [128, W]) for c in range(NCK)] for p_ in range(2)]
    Kfm2 = [[k.sb(f"Kfm{p_}{c}", [128, W]) for c in range(NCK)] for p_ in range(2)]
    sqp = k.sb("sqp", [128, W])
    tmpp = k.sb("tmpp", [128, W])
    FT = [k.sb(f"FT{h}", [64, 4, 128], FR) for h in range(NH)]
    A5 = [k.sb(f"A5_{h}", [128, 640], FR) for h in range(NH)]
    NL = [k.sb(f"NL_{h}", [128, 256], FR) for h in range(NH)]
    PQ = [k.sb(f"PQ_{h}", [128, 128], FR) for h in range(NH)]
    W1 = k.sb("W1", [128, W], FR)
    U1 = k.sb("U1", [128, W])
    ysb = k.sb("ysb", [128, W])
    yc = k.sb("yc", [128, W])
    m4 = k.sb("m4", [128, NH])
    r4 = k.sb("r4", [128, NH])
    ot = [k.sb(f"ot{i}", [128, W]) for i in range(2)]
    B = [k.ps(f"psB{i}", [128, 512]) for i in range(8)]
    bk = lambda i: f'psB{i}'
    v3 = lambda t: t.rearrange("p (h j) -> p h j", h=NH)
    bc4 = lambda t: t.unsqueeze(2).broadcast_to([128, NH, 64])


    S0, S1, C0, C1 = 6, 7, 4, 5

    def tile(i):
        b = i % 2
        pm, lm = pm2[b], lm2[b]
        kpm, klm = f'pm{b}', f'lm{b}'
        At, Bs, Ks, Rt, gv, vr, bon, E1T, Bf, Kf = At2[b], Bs2[b], Ks2[b], Rt2[b], gv2[b], vr2[b], bon2[b], E1T2[b], Bfm2[b], Kfm2[b]
        kAt, kBs, kKs, kRt, kgv, kvr, kbon, kE1T, kBf, kKf = (f'{n_}{b}' for n_ in ('At', 'Bs', 'Ks', 'Rt', 'gv', 'vr', 'bon', 'E1T', 'Bf', 'Kf'))
        rows = slice(i * 128, (i + 1) * 128)
        PK, PPK, LTK, LPK = [], [], [], []
        for q in range(3):
            cq = slice(q * W, (q + 1) * W)
            k.dma('sp', pt[b][:, cq], prkv[q][rows, :], w=[f'pt{q}'])
            PK.append(f'pt{q}')
            if i == 0:
                k.dma('sp', pp[b][1:128, cq], prkv[q][0:127, :], w=[f'pp{q}'])
            else:
                k.dma('sp', pp[b][:, cq], prkv[q][i * 128 - 1:i * 128 + 127, :], w=[f'pp{q}'])
            PPK.append(f'pp{q}')
            nr = pls[q].shape[0]
            k.dma('sp', lt[b][0:nr, q, :], pls[q][:, rows], w=[f'lt{q}'])
            LTK.append(f'lt{q}')
            if i == 0:
                k.dma('sp', lp[b][0:nr, q, 1:128], pls[q][:, 0:127], w=[f'lp{q}'])
            else:
                k.dma('sp', lp[b][0:nr, q, :], pls[q][:, i * 128 - 1:i * 128 + 127], w=[f'lp{q}'])
            LPK.append(f'lp{q}')
        if i == 0:
            k.memset('pool', pp[b][0:1, :], 0.0, ['ppz'])
            k.memset('pool', lp[b][:, :, 0:1], 0.0, ['lpz'])
            PPK.append('ppz')
            LPK.append('lpz')
        k.tt('dve', pm[:], pp[b][:], pt[b][:], ALU.subtract, PPK + PK, [kpm])
        k.tt('dve', pm[:], pm[:], mu1bc[:], ALU.mult, [kpm, 'mu1bc'], [kpm])
        k.tt('dve', pm[:], pm[:], pt[b][:], ALU.add, [kpm] + PK, [kpm])
        r_, k_, v_ = pm[:, 0:W], pm[:, W:2 * W], pm[:, 2 * W:3 * W]
        LK = LTK + LPK
        k.tt('dve', lm[:], lp[b][:], lt[b][:], ALU.subtract, LK, [klm])
        for blk in range(3):
            k.stt(lm[:, blk, :], lm[:, blk, :], muls[:, blk:blk + 1], lt[b][:, blk, :], ALU.mult, ALU.add,
                  [klm, 'muls'] + LK, [klm])
        k.act(lm[0:64, 0, :], lm[0:64, 0, :], AF.Tanh, [klm], [klm])
        k.act(lm[:, 2, :], lm[:, 2, :], AF.Sigmoid, [klm], [klm])
        yield
        k.cp('act', vr[:], v_, [kpm], [kvr])
        k.mm(B[S0][:, 0:W], lm[0:64, 0, :], w2s[:], True, True, [klm, 'w2s'], [bk(S0)])
        k.mm(B[S1][:, 0:W], lm[0:64, 1, :], a2s[:], True, True, [klm, 'a2s'], [bk(S1)])
        k.tt('dve', sw[:], B[S0][:, 0:W], w0bc[:], ALU.add, [bk(S0), VK[0]], ['sw'])
        k.act(sw[:], sw[:], AF.Sigmoid, ['sw'], ['sw'])
        k.tt('dve', av[:], B[S1][:, 0:W], a0bc[:], ALU.add, [bk(S1), VK[1]], ['av'])
        k.act(av[:], av[:], AF.Sigmoid, ['av'], ['av'])
        k.mm(B[S0][:, 0:W], lm[:, 2, :], g2s[:], True, True, [klm, 'g2s'], [bk(S0)])
        k.cp('act', gv[:], B[S0][:, 0:W], [bk(S0)], [kgv])
        yield
        k.tt('dve', kkr[:], k_, kkbc[:], ALU.mult, [kpm, VK[2]], ['kkr'])
        k.tt('dve', sq[:], kkr[:], kkr[:], ALU.mult, ['kkr'], ['sq'])
        k.P.op('dve', lambda e: e.tensor_reduce(out=s4[:], in_=v3(sq[:]), axis=AX.X, op=ALU.add), reads=['sq'], writes=['s4'])
        k.act(s4[:], s4[:], AF.Sqrt, ['s4'], ['s4'])
        k.ts('dve', s4[:], s4[:], 1e-12, None, ALU.max, None, ['s4'], ['s4'])
        k.recip(rn[:], s4[:], ['s4'], ['rn'])
        k.ts('dve', rn[:], rn[:], -1.0, None, ALU.mult, None, ['rn'], ['rn'])
        k.tt('dve', v3(nkk[:]), v3(kkr[:]), bc4(rn[:]), ALU.mult, ['kkr', 'rn'], ['nkk'])
        k.stt(tmp[:], av[:], -1.0, kabc[:], ALU.add, ALU.mult, ['av', VK[3]], ['tmp'])
        k.stt(kmod[:], tmp[:], 1.0, k_, ALU.add, ALU.mult, ['tmp', kpm], ['kmod'])
        k.stt(kka[:], nkk[:], -1.0, av[:], ALU.mult, ALU.mult, ['nkk', 'av'], ['kka'])
        k.tt('dve', tmp[:], r_, kmod[:], ALU.mult, [kpm, 'kmod', 'tmp'], ['tmp'])
        k.tt('dve', tmp[:], tmp[:], rkbc[:], ALU.mult, ['tmp', VK[4]], ['tmp'])
        k.P.op('dve', lambda e: e.tensor_reduce(out=bon[:], in_=v3(tmp[:]), axis=AX.X, op=ALU.add), reads=['tmp'], writes=[kbon])
        k.mm(B[S1][:, 0:W], triw[:, 0, :], sw[:], True, True, ['triw', 'sw'], [bk(S1)])
        k.mm(B[S0][:, 0:W], triw[:, 1, :], sw[:], True, True, ['triw', 'sw'], [bk(S0)])
        k.act(E1[:], B[S1][:, 0:W], AF.Exp, [bk(S1)], ['E1'])
        k.act(E2[:], B[S1][:, 0:W], AF.Exp, [bk(S1)], ['E2'], scale=-1.0)
        k.act(E3[:], B[S0][:, 0:W], AF.Exp, [bk(S0)], ['E3'])
        k.mm(B[S1][:, 0:W], triw[:, 2, :], sw[:], True, True, ['triw', 'sw'], [bk(S1)])
        k.act(E4[:], B[S1][:, 0:W], AF.Exp, [bk(S1)], ['E4'])
        for g in range(2):
            for hl in range(4):
                h = 4 * g + hl
                k.mm(B[S0 + g][0:64, hl * 128:(hl + 1) * 128], sw[:, h * 64:(h + 1) * 64], triw[:, 0, :], True, True,
                     ['sw', 'triw'], [bk(S0 + g)])
        for g in range(2):
            k.act(E1T[:, 4 * g:4 * g + 4, :].rearrange("p a t -> p (a t)"), B[S0 + g][0:64, :], AF.Exp, [bk(S0 + g)], [kE1T])
        yield
        k.tt('dve', At[:], nkk[:], E3[:], ALU.mult, ['nkk', 'E3'], [kAt])
        k.tt('dve', Bs[:], kka[:], E2[:], ALU.mult, ['kka', 'E2'], [kBs])
        k.tt('dve', Ks[:], kmod[:], E2[:], ALU.mult, ['kmod', 'E2'], [kKs])
        k.tt('dve', Rt[:], r_, E1[:], ALU.mult, [kpm, 'E1'], [kRt])
        for c in range(NCK):
            k.stt(Bf[c][:], kka[:], rowm[:, c:c + 1], E4[:], ALU.mult, ALU.mult, ['kka', 'E4', 'rowm'], [kBf])
            k.stt(Kf[c][:], kmod[:], rowm[:, c:c + 1], E4[:], ALU.mult, ALU.mult, ['kmod', 'E4', 'rowm'], [kKf])
        yield
        for g in range(2):
            HS = list(range(4 * g, 4 * g + 4))
            for h in HS:
                hl = h % 4
                cs_ = slice(h * 64, (h + 1) * 64)
                for q, (src, key) in enumerate([(At, kAt), (Bs, kBs), (Ks, kKs), (Rt, kRt)]):
                    k.tr(B[hl][0:64, q * 128:(q + 1) * 128], src[:, cs_], k.identf[:], [key], [bk(hl)])
            for h in HS:
                hl = h % 4
                k.cp('act' if h % 2 else 'dve', FT[h][:].rearrange("p a t -> p (a t)"), B[hl][0:64, :], [bk(hl)], [f'FT{h}'])
            for h in HS:
                hl = h % 4
                AtT, BsT, KsT, RtT = (FT[h][:, q, :] for q in range(4))
                k.mm(B[hl][:, 0:128], BsT, AtT, True, True, [f'FT{h}'], [bk(hl)])
                k.mm(B[hl][:, 128:256], AtT, BsT, True, True, [f'FT{h}'], [bk(hl)])
                k.mm(B[hl][:, 256:384], KsT, AtT, True, True, [f'FT{h}'], [bk(hl)])
            for h in HS:
                hl = h % 4
                k.tt('dve', A5[h][:, 0:384], B[hl][:, 0:384], mask5[:, 0:384], ALU.mult, [bk(hl), 'mask5'], [f'A5_{h}'])
            for h in HS:
                hl = h % 4
                AtT, BsT, KsT, RtT = (FT[h][:, q, :] for q in range(4))
                k.mm(B[hl][:, 0:128], BsT, RtT, True, True, [f'FT{h}'], [bk(hl)])
                k.mm(B[hl][:, 128:256], KsT, RtT, True, True, [f'FT{h}'], [bk(hl)])
            for h in HS:
                hl = h % 4
                k.tt('dve', A5[h][:, 384:640], B[hl][:, 0:256], mask5[:, 384:640], ALU.mult, [bk(hl), 'mask5'], [f'A5b_{h}'])
                k.cp('act', NL[h][:], rd(A5[h][:, 0:256]), [f'A5_{h}'], [f'NL_{h}'])
                k.tt('dve', PQ[h][:, 0:128], rd(A5[h][:, 0:128]), k.identf[:], ALU.add, [f'A5_{h}', 'ident'], [f'PQ_{h}'])
            for lev in range(nlev):
                last = (lev == nlev - 1)
                for h in HS:
                    hl = h % 4
                    N_, L_ = NL[h][:, 0:128], NL[h][:, 128:256]
                    k.mm(B[hl][:, 0:128], L_, N_, True, True, [f'NL_{h}'], [bk(hl)])
                    k.mm(B[hl][:, 128:256], N_, L_, True, True, [f'NL_{h}'], [bk(hl)])
                for h in HS:
                    hl = h % 4
                    k.cp('act', NL[h][:], B[hl][:, 0:256], [bk(hl)], [f'NL_{h}'])
                for h in HS:
                    hl = h % 4
                    k.mm(B[hl][:, 256:384], NL[h][:, 128:256], PQ[h][:, 0:128], True, True, [f'NL_{h}', f'PQ_{h}'], [bk(hl)])
                for h in HS:
                    hl = h % 4
                    k.tt('dve', PQ[h][:, 0:128], B[hl][:, 256:384], rd(PQ[h][:, 0:128]), ALU.add, [bk(hl), f'PQ_{h}'], [f'PQ_{h}'])
            yield
        for h in range(NH):
            k.mm(B[C0][:, h * 64:(h + 1) * 64], A5[h][:, 256:384], vr[:, h * 64:(h + 1) * 64], True, True, [f'A5_{h}', kvr], [bk(C0)])
        k.cp('act', W1[:], B[C0][:, 0:W], [bk(C0)], ['W1'])
        for h in range(NH):
            k.mm(B[C1][:, h * 64:(h + 1) * 64], PQ[h][:, 0:128], W1[:, h * 64:(h + 1) * 64], True, True,
                 [f'PQ_{h}', 'W1'], [bk(C1)])
        k.cp('act', U1[:], B[C1][:, 0:W], [bk(C1)], ['U1'])
        for c in range(NCK):
            cr = slice(c * CH, (c + 1) * CH)
            for h in range(NH):
                k.mm(B[C0][:, h * 64:(h + 1) * 64], FT[h][:, 0, :], ST[h][:], True, True, [f'FT{h}', f'ST{h}'], [bk(C0)])
            k.cp('act', P1s[cr, :], B[C0][cr, 0:W], [bk(C0)], ['P1s'])
            for h in range(NH):
                k.mm(B[C0][:, h * 64:(h + 1) * 64], PQ[h][:, :], P1s[:, h * 64:(h + 1) * 64], True, True,
                     [f'PQ_{h}', 'P1s'], [bk(C0)])
            k.tt('dve', Us[cr, :], B[C0][cr, 0:W], U1[cr, :], ALU.add, [bk(C0), 'U1'], ['Us'])
            for h in range(NH):
                hc_ = slice(h * 64, (h + 1) * 64)
                k.mm(B[C0][:, hc_], FT[h][:, 3, :], ST[h][:], True, False, [f'FT{h}', f'ST{h}'], [bk(C0)])
                k.mm(B[C0][:, hc_], A5[h][:, 384:512], Us[:, hc_], False, False, [f'A5b_{h}', 'Us'], [bk(C0)])
                k.mm(B[C0][:, hc_], A5[h][:, 512:640], vr[:, hc_], False, True, [f'A5b_{h}', kvr], [bk(C0)])
            k.cp('act', ysb[cr, :], B[C0][cr, 0:W], [bk(C0)], ['ysb'])
            for h in range(NH):
                hc_ = slice(h * 64, (h + 1) * 64)
                k.mm(B[C1][0:64, hc_], Bf[c][:, hc_], rdc(Us[:, hc_]), True, False, [kBf, 'Us'], [bk(C1)])
                k.mm(B[C1][0:64, hc_], Kf[c][:, hc_], rd(vr[:, hc_]), False, True, [kKf, kvr], [bk(C1)])
            for h in range(NH):
                hc_ = slice(h * 64, (h + 1) * 64)
                k.stt(ST[h][:], rdc(ST[h][:]), E1T[:, h, (c + 1) * CH - 1:(c + 1) * CH], B[C1][0:64, hc_], ALU.mult, ALU.add,
                      [f'ST{h}', kE1T, bk(C1)], [f'ST{h}'])
        k.P.op('dve', lambda e: e.tensor_reduce(out=m4[:], in_=v3(ysb[:]), axis=AX.X, op=ALU.add), reads=['ysb'], writes=['m4'])
        k.ts('dve', m4[:], m4[:], -1.0 / 64.0, None, ALU.mult, None, ['m4'], ['m4'])
        k.tt('dve', v3(yc[:]), v3(ysb[:]), bc4(m4[:]), ALU.add, ['ysb', 'm4'], ['yc'])
        k.tt('dve', sqp[:], yc[:], yc[:], ALU.mult, ['yc'], ['sqp'])
        k.P.op('dve', lambda e: e.tensor_reduce(out=r4[:], in_=v3(sqp[:]), axis=AX.X, op=ALU.add), reads=['sqp'], writes=['r4'])
        k.ts('dve', r4[:], r4[:], 1.0 / 64.0, GN_EPS, ALU.mult, ALU.add, ['r4'], ['r4'])
        k.act(r4[:], r4[:], AF.Sqrt, ['r4'], ['r4'])
        k.recip(r4[:], r4[:], ['r4'], ['r4'])
        k.tt('dve', v3(yc[:]), v3(yc[:]), bc4(r4[:]), ALU.mult, ['yc', 'r4'], ['yc'])
        k.tt('dve', yc[:], yc[:], lngbc[:], ALU.mult, ['yc', VK[5]], ['yc'])
        k.tt('dve', yc[:], yc[:], lnbbc[:], ALU.add, ['yc', VK[6]], ['yc'])
        k.tt('dve', v3(tmpp[:]), v3(rd(vr[:])), bc4(bon[:]), ALU.mult, [kvr, kbon], ['tmpp'])
        k.tt('dve', yc[:], yc[:], tmpp[:], ALU.add, ['yc', 'tmpp'], ['yc'])
        k.tt('dve', ot[b][:], yc[:], gv[:], ALU.mult, ['yc', kgv], [f'ot{b}'])
        k.dma('pool', oc[rows, :], ot[b][:], r=[f'ot{b}'], final=True)

    gens = {}

    def adv(j):
        if 0 <= j < NT:
            try:
                next(gens[j])
            except StopIteration:
                pass

    for step in range(NT + 2):
        if step < NT:
            gens[step] = tile(step)
            adv(step)
        for r_i in range(3):
            adv(step - 1)
            adv(step - 2)
    return k.finish()


def rwkv_consts(CH=64):
    c = -math.exp(-0.5)
    blk = np.kron(np.eye(128 // CH), np.ones((CH, CH)))
    s_idx = np.arange(128)[:, None]
    t_idx = np.arange(128)[None, :]
    triw = np.stack([c * blk * (s_idx <= t_idx), c * blk * (s_idx < t_idx), c * blk * (s_idx > t_idx)]).astype(np.float32)
    lt_, le_, gt_ = blk * (s_idx < t_idx), blk * (s_idx <= t_idx), blk * (t_idx < s_idx)
    mask5 = np.concatenate([lt_, gt_, lt_, le_, le_], 1).astype(np.float32)
    rowm = np.stack([(np.arange(128) < 64), (np.arange(128) >= 64)], 1).astype(np.float32) if CH == 64 else np.ones((128, 2), np.float32)
    return dict(ident=np.eye(128, dtype=np.float32), triw=triw, mask5=mask5, rowm=rowm)


def rwkv_host_inputs(s, p_rwkv, prm, NH=4, CH=64):
    L = p_rwkv.shape[0]
    cs = slice(64 * NH * s, 64 * NH * (s + 1))
    r_, w1, k_, v_, a1, g1 = np.split(p_rwkv, np.cumsum([512, 64, 512, 512, 64])[:5], axis=-1)
    mu = prm['rwkv_mu']
    mur, muw1, muk, muv, mua1, mug1 = np.split(mu, np.cumsum([512, 64, 512, 512, 64])[:5])
    zm = np.zeros(64, np.float32)
    mul = np.concatenate([muw1, zm, mua1, zm, mug1]).reshape(3, 128).T
    vecs = np.stack([prm['rwkv_w0'][cs], prm['rwkv_a0'][cs], prm['rwkv_k_k'][cs], prm['rwkv_k_a'][cs],
                     prm['rwkv_r_k'].reshape(-1)[cs], prm['rwkv_ln_gain'][cs], prm['rwkv_ln_bias'][cs]])
    c_ = np.ascontiguousarray
    d = dict(pr=c_(r_[:, cs]), pk=c_(k_[:, cs]), pv=c_(v_[:, cs]),
             mu1=c_(np.concatenate([mur[cs], muk[cs], muv[cs]])),
             plw=c_(w1.T), pla=c_(a1.T), plg=c_(g1.T), mul=c_(mul),
             w2=c_(prm['rwkv_w2'][:, cs]), a2=c_(prm['rwkv_a2'][:, cs]),
             g2=c_(prm['rwkv_g2'][:, cs]), vecs=c_(vecs))
    d.update(rwkv_consts(CH))
    return d


FM0 = [(0, 128, 0), (128, 128, 128), (256, 128, 256), (384, 128, 384), (1536, 16, 512)] + \
      [(1552 + j * 128, 128, 528 + j * 128) for j in range(4)]
NF0 = 1040
FM1 = [(512, 64, 0), (1600, 64, 64), (1664, 128, 128)] + [(1792 + j * 128, 128, 256 + j * 128) for j in range(8)]
NF1 = 1280


def host_params(inp):
    c_ = lambda a: np.ascontiguousarray(np.asarray(a), dtype=np.float32)
    P = {}
    P['ident'] = np.eye(128, dtype=np.float32)
    P['triu'] = np.triu(np.ones((128, 128), np.float32))
    P['trigt'] = np.tril(np.ones((128, 128), np.float32), -1)
    for l in range(2):
        for j in range(7):
            P[f'g{l}_{j}'] = c_(inp['norm_gain'][l][j])
        for nm in ('xa_wq', 'xa_wk', 'xa_wv', 'xa_wo', 'mlp_w1', 'mlp_w2'):
            P[f'{nm}{l}'] = c_(inp[nm][l])
    P['w_in0'] = c_(inp['ab_w_in'][0])
    P['w_in1'] = c_(inp['cd_w_in'][0])
    P['w_out0'] = c_(inp['ab_w_out'][0])
    P['w_out1'] = c_(inp['cd_w_out'][0])
    P['wglu'] = c_(inp['s5_w_glu'][0])
    P['bglu'] = c_(inp['s5_b_glu'][0])
    prm0 = {k_: np.asarray(inp[k_][0]) for k_ in inp if k_.startswith('s5_') or k_.startswith('gla_')}
    prm1 = {k_: np.asarray(inp[k_][0]) for k_ in inp if k_.startswith('rwkv_') or k_.startswith('lru_')}
    for s in range(2):
        cs = slice(s * 128, (s + 1) * 128)
        P[f'gla_w2_{s}'] = c_(prm0['gla_w_decay2'][:, cs])
        P[f'gla_bd_{s}'] = c_(prm0['gla_b_decay'][None, cs])
        P[f'gla_gn_{s}'] = c_(prm0['gla_norm_gain'][2 * s:2 * s + 2].reshape(256))
        d = s5_host_inputs(s, np.zeros((2, 512), np.float32), prm0)
        for nm in ('lam_re', 'lam_im', 'lstep', 'Bre', 'Bim', 'Cre', 'Cim', 'dsk'):
            P[f's5_{nm}_{s}'] = c_(d[nm])
        P['iota_p'] = c_(d['iota_p'])
        P['iota_f'] = c_(d['iota_f'])
        if s == 0:
            d = rwkv_host_inputs(0, np.zeros((2, 1792), np.float32), prm1, 8, 64)
            for nm in ('mu1', 'mul', 'w2', 'a2', 'g2', 'vecs'):
                P[f'rw_{nm}'] = c_(d[nm])
            for nm in ('triw', 'mask5', 'rowm'):
                P[f'rw_{nm}'] = c_(d[nm])
        d = lru_host_inputs(s, np.zeros((2, 512), np.float32), np.zeros((2, 512), np.float32), prm1)
        for nm in ('cw', 'cb', 'Wa', 'Wx', 'ba', 'bx', 'lam'):
            P[f'lru_{nm}_{s}'] = c_(d[nm])
    return P


def build_fused(P, L):
    k = K(fused=True)
    X = {nm: k.xin(nm, a.shape) for nm, a in P.items()}
    x = k.xin('x', [L, D])
    mem = k.xin('mem', [256, D])
    out = k.xout('out', [L, D])
    proj0 = k.scratch('proj0', [L, 2064])
    PT0 = k.scratch('PT0', [NF0, L])
    proj1 = k.scratch('proj1', [L, 2816])
    PT1 = k.scratch('PT1', [NF1, L])
    o = k.scratch('o', [L, D])
    odT = k.scratch('odT', [512, L])
    h1 = k.scratch('h1', [L, D])
    h2 = k.scratch('h2', [L, D])
    h3 = k.scratch('h3', [L, D])

    def cblock(l, hin, hout, glu, ob_fm):
        io = dict(oa=o[:, 0:512], hin=hin, wout=X[f'w_out{l}'], g1=X[f'g{l}_1'], ident=X['ident'], hout=h1)
        if ob_fm:
            io['obT'] = odT
        else:
            io['ob'] = o[:, 512:1024]
        if glu:
            io.update(wglu=X['wglu'], bglu=X['bglu'])
        k.begin_phase(f'C1_{l}', io)
        build_C1(L, glu, k=k, ob_fm=ob_fm)
        k.begin_phase(f'C2_{l}', dict(hin=h1, mem=mem, wq=X[f'xa_wq{l}'], wk=X[f'xa_wk{l}'], wv=X[f'xa_wv{l}'], wo=X[f'xa_wo{l}'],
                                      g2=X[f'g{l}_2'], g3=X[f'g{l}_3'], g6=X[f'g{l}_6'], ident=X['ident'], hout=h2))
        build_C2(L, k=k)
        k.begin_phase(f'C3_{l}', dict(hin=h2, w1=X[f'mlp_w1{l}'], w2=X[f'mlp_w2{l}'], g4=X[f'g{l}_4'], g5=X[f'g{l}_5'],
                                      ident=X['ident'], hout=hout))
        build_C3(L, k=k)

    k.begin_phase('A0', dict(x=x, gain=X['g0_0'], W=X['w_in0'], ident=X['ident'], out=proj0, outT=PT0))
    build_A2(L, 2064, FM0, NF0, k=k)
    for s in range(2):
        io_g = dict(qT=PT0[s * 128:(s + 1) * 128, :], kT=PT0[256 + s * 128:256 + (s + 1) * 128, :],
                    ktok=proj0[:, 256 + s * 128:256 + (s + 1) * 128], v=proj0[:, 512 + s * 256:512 + (s + 1) * 256],
                    gate=proj0[:, 1024 + s * 256:1024 + (s + 1) * 256], dlrT=PT0[512:528, :],
                    w2=X[f'gla_w2_{s}'], bdec=X[f'gla_bd_{s}'], gn=X[f'gla_gn_{s}'], triu=X['triu'],
                    trigt=X['trigt'], oa=o[:, s * 256:(s + 1) * 256])
        k.begin_phase(f'GLA{s}', io_g)
        build_GLA(L, k=k)
    for s in range(2):
        io_s = dict(uT=PT0[528 + s * 256:528 + (s + 1) * 256, :], u=proj0[:, 1552 + s * 256:1552 + (s + 1) * 256],
                    triu=X['triu'], iota_p=X['iota_p'], iota_f=X['iota_f'], y=o[:, 512 + s * 256:512 + (s + 1) * 256])
        for nm in ('lam_re', 'lam_im', 'lstep', 'Bre', 'Bim', 'Cre', 'Cim', 'dsk'):
            io_s[nm] = X[f's5_{nm}_{s}']
        k.begin_phase(f'S5{s}', io_s)
        build_S5(L, k=k)
    cblock(0, x, h3, True, False)
    k.begin_phase('A1', dict(x=h3, gain=X['g1_0'], W=X['w_in1'], ident=X['ident'], out=proj1, outT=PT1))
    build_A2(L, 2816, FM1, NF1, k=k)
    io = dict(pr=proj1[:, 0:512], pk=proj1[:, 576:1088], pv=proj1[:, 1088:1600], plw=PT1[0:64, :], pla=PT1[64:128, :],
              plg=PT1[128:256, :], ident=X['ident'], triw=X['rw_triw'], mask5=X['rw_mask5'], rowm=X['rw_rowm'], oc=o[:, 0:512])
    for nm in ('mu1', 'mul', 'w2', 'a2', 'g2', 'vecs'):
        io[nm] = X[f'rw_{nm}']
    k.begin_phase('RW', io)
    build_RWKVP(L, k=k, CH=64)
    streams = []
    for s in range(2):
        io = dict(xbT=PT1[256 + s * 256:256 + (s + 1) * 256, :], gateT=PT1[768 + s * 256:768 + (s + 1) * 256, :],
                  odT=odT[s * 256:(s + 1) * 256, :])
        for nm in ('cw', 'cb', 'Wa', 'Wx', 'ba', 'bx', 'lam'):
            io[nm] = X[f'lru_{nm}_{s}']
        streams.append((f'l{s}_', io, lambda kk: gen_LRU(L, kk)))
    k.begin_phase('LRU', {})
    run_streams(k, streams)
    k.finish()
    cblock(1, h3, out, False, True)
    return k.finish_program()


BATCH, SEQ = 4, 4096
_CACHE = {}


def kernel(**inp):
    inp = {k_: np.asarray(v_) for k_, v_ in inp.items()}
    P = host_params(inp)
    if 'nc' not in _CACHE:
        _CACHE['nc'] = build_fused(P, SEQ)
    nc = _CACHE['nc']
    maps = []
    for b in range(BATCH):
        m = dict(P)
        m['x'] = np.ascontiguousarray(inp['x'][b], dtype=np.float32)
        m['mem'] = np.ascontiguousarray(inp['mem'][b], dtype=np.float32)
        maps.append(m)
    res = run_bass_kernel_spmd(nc, maps, core_ids=list(range(BATCH))).results
    return np.ascontiguousarray(np.stack([res[b]['out'] for b in range(BATCH)]).astype(np.float32))
```

```python
import os
import math
from contextlib import ExitStack


import numpy as np
import concourse.bass as bass
import concourse.mybir as mybir
from concourse.bass_utils import run_bass_kernel_spmd

F32 = mybir.dt.float32
BF16 = mybir.dt.bfloat16
I32 = mybir.dt.int32
AF = mybir.ActivationFunctionType
ALU = mybir.AluOpType
AX = mybir.AxisListType

ENGS = ['pe', 'act', 'dve', 'pool', 'sp']
NDMA_SLOTS = 8
SAME_ENGINE_SYNC = os.environ.get("NOSELF", "0") != "1"


class Prog:
    def __init__(self, nc):
        self.nc = nc
        self.ops = {e: [] for e in ENGS}
        self.cnt = {e: 0 for e in ENGS}
        self.last_w = {}
        self.readers = {}
        self.seen = {e: {} for e in ENGS}
        self.dma_n = {e: 0 for e in ENGS}
        self.dma_tok = {e: [None] * NDMA_SLOTS for e in ENGS}
        self.final_tokens = []
        from contextlib import ExitStack
        self.sem_stack = ExitStack()
        self.sems = {}
        for e in ['pe', 'act', 'dve', 'pool']:
            self.sems[('c', e)] = self.sem_stack.enter_context(nc.semaphore("s_c_" + e))
        for q in ['sp', 'pool']:
            for sl in range(NDMA_SLOTS):
                self.sems[('d', q, sl)] = self.sem_stack.enter_context(nc.semaphore(f"s_d_{q}_{sl}"))

    def barrier(self):
        toks = []
        for e in ['pe', 'act', 'dve', 'pool']:
            if self.cnt[e] > 0:
                toks.append((('c', e), self.cnt[e]))
        for q in ENGS:
            for t in self.dma_tok[q]:
                if t is not None:
                    toks.append(t)
        for e in ENGS:
            waits = []
            for (sem, val) in toks:
                if sem == ('c', e):
                    continue
                if self.seen[e].get(sem, 0) >= val:
                    continue
                waits.append((sem, val))
                self.seen[e][sem] = val
            if waits:
                self.ops[e].append((waits, None, None))
        self.last_w = {}
        self.readers = {}

    def _deps(self, eng, reads, writes):
        toks = []
        for r in reads:
            t = self.last_w.get(r)
            if t is not None:
                toks.append(t)
        for w in writes:
            t = self.last_w.get(w)
            if t is not None:
                toks.append(t)
            toks.extend(self.readers.get(w, []))
        need = {}
        for (sem, val) in toks:
            if not SAME_ENGINE_SYNC and sem == ('c', eng):
                continue
            if sem == ('c', 'pe') and eng == 'pe':
                continue
            if self.seen[eng].get(sem, 0) >= val:
                continue
            if need.get(sem, 0) < val:
                need[sem] = val
        for sem, val in need.items():
            self.seen[eng][sem] = val
        return list(need.items())

    def _commit(self, tok, reads, writes):
        for w in writes:
            self.last_w[w] = tok
            self.readers[w] = []
        for r in reads:
            if r in writes:
                continue
            self.readers.setdefault(r, []).append(tok)

    def op(self, eng, fn, reads=(), writes=()):
        self.nrec = getattr(self, 'nrec', 0) + 1
        if self.nrec > int(os.environ.get("MAXOPS", "100000000")):
            return None
        kp = getattr(self, 'key_prefix', '')
        reads = [r if r.startswith('ps') else kp + r for r in reads]
        writes = [w if w.startswith('ps') else kp + w for w in writes]
        pk = getattr(self, 'ps_prefix', '')
        reads = [('ps' + pk + r[2:]) if r.startswith('ps') else r for r in reads]
        writes = [('ps' + pk + w[2:]) if w.startswith('ps') else w for w in writes]
        writes = list(writes) + [r for r in reads if r.startswith('ps') and r not in writes]
        waits = self._deps(eng, reads, writes)
        self.cnt[eng] += 1
        tok = (('c', eng), self.cnt[eng])
        self.ops[eng].append((waits, fn, tok))
        self._commit(tok, reads, writes)
        return tok

    def dma(self, q, out, in_, reads=(), writes=(), final=False, **kw):
        self.nrec = getattr(self, 'nrec', 0) + 1
        if self.nrec > int(os.environ.get("MAXOPS", "100000000")):
            return None
        kp = getattr(self, 'key_prefix', '')
        reads = [kp + r for r in reads]
        writes = [kp + w for w in writes]
        waits = self._deps(q, reads, writes)
        n = self.dma_n[q]
        slot = n % NDMA_SLOTS
        prev = self.dma_tok[q][slot]
        if prev is not None and self.seen[q].get(prev[0], 0) < prev[1]:
            waits.append(prev)
            self.seen[q][prev[0]] = prev[1]
        tok = (('d', q, slot), 16 * (n // NDMA_SLOTS + 1))
        self.dma_n[q] += 1
        self.dma_tok[q][slot] = tok

        def fn(e, out=out, in_=in_, kw=kw):
            return e.dma_start(out=out, in_=in_, **kw)
        self.ops[q].append((waits, fn, tok))
        self._commit(tok, reads, writes)
        if final:
            self.final_tokens.append(tok)
        return tok

    def emit(self, last=True):
        nc = self.nc
        sems = self.sems
        with nc.Block() as block:
            final = list(self.final_tokens) if last else []

            def run(e, name):
                for waits, fn, tok in self.ops[name]:
                    for (s, v) in waits:
                        e.wait_ge(sems[s], v)
                    if fn is None:
                        continue
                    inst = fn(e)
                    inc = 16 if tok[0][0] == 'd' else 1
                    inst.then_inc(sems[tok[0]], inc)
                if name == 'sp':
                    for (s, v) in final:
                        e.wait_ge(sems[s], v)
                self.ops[name] = []

            @block.tensor
            def _(e):
                run(e, 'pe')

            @block.scalar
            def _(e):
                run(e, 'act')

            @block.vector
            def _(e):
                run(e, 'dve')

            @block.gpsimd
            def _(e):
                run(e, 'pool')

            @block.sync
            def _(e):
                run(e, 'sp')
        if last:
            self.sem_stack.close()


D = 1024
KC = 8
EPS = 1e-6


class K:
    def __init__(self, fused=False):
        self.nc = bass.Bass("TRN2", target_bir_lowering=False)
        self.st = ExitStack()
        self.P = Prog(self.nc)
        self.n = 0
        self.fused = fused
        self.io = {}
        self.pfx = ""

    def begin_phase(self, name, io):
        self.pfx = name + "_"
        self.io = io
        self.st = ExitStack()
        for a in ('wstage', 'rr_cache', 'identf', 'identb'):
            if hasattr(self, a):
                delattr(self, a)

    def scratch(self, name, shape, dt=F32):
        return self.nc.dram_tensor(name, list(shape), dt, kind="Internal").ap()

    def xin(self, name, arr_shape, dt=F32):
        return self.nc.dram_tensor(name, list(arr_shape), dt, kind="ExternalInput").ap()

    def xout(self, name, arr_shape, dt=F32):
        return self.nc.dram_tensor(name, list(arr_shape), dt, kind="ExternalOutput").ap()

    def din(self, name, shape, dt=F32):
        if self.fused:
            ap = self.io[name]
            assert list(ap.shape) == list(shape), (name, ap.shape, shape)
            return ap
        return self.nc.dram_tensor(name, list(shape), dt, kind="ExternalInput").ap()

    def dout(self, name, shape, dt=F32):
        if self.fused:
            ap = self.io[name]
            assert list(ap.shape) == list(shape), (name, ap.shape, shape)
            return ap
        return self.nc.dram_tensor(name, list(shape), dt, kind="ExternalOutput").ap()

    def sb(self, name, shape, dt=F32):
        pers = getattr(self, 'persist', None)
        if pers is not None and (self.pfx + name) in pers:
            return pers[self.pfx + name]
        return self.st.enter_context(self.nc.sbuf_tensor(self.pfx + name, list(shape), dt))

    def push_scope(self, persistent):
        self.persist = getattr(self, 'persist', None) or {}
        for (name, shape, dt) in persistent:
            self.persist[self.pfx + name] = self.st.enter_context(self.nc.sbuf_tensor(self.pfx + name, list(shape), dt))
        self._st_saved = self.st
        self.st = ExitStack()

    def pop_scope(self):
        self.P.barrier()
        self.P.emit(last=False)
        self.st.close()
        self.st = self._st_saved

    def ps(self, name, shape, dt=F32):
        return self.st.enter_context(self.nc.psum_tensor(self.pfx + name, list(shape), dt))

    def finish(self, last=True):
        if self.fused:
            self.P.barrier()
            self.P.emit(last=False)
            self.st.close()
            return None
        self.P.emit()
        self.st.close()
        return self.nc

    def finish_program(self):
        self.P.emit(last=True)
        return self.nc

    def mm(self, out, lhsT, rhs, start, stop, r, w):
        self.P.op('pe', lambda e: e.matmul(out, lhsT=lhsT, rhs=rhs, start=start, stop=stop), reads=r, writes=w)

    def tr(self, out, in_, ident, r, w):
        self.P.op('pe', lambda e: e.transpose(out=out, in_=in_, identity=ident), reads=list(r) + ['ident'], writes=w)

    def act(self, out, in_, func, r, w, **kw):
        self.P.op('act', lambda e: e.activation(out=out, in_=in_, func=func, **kw), reads=r, writes=w)

    def tt(self, eng, out, in0, in1, op, r, w):
        self.P.op(eng, lambda e: e.tensor_tensor(out=out, in0=in0, in1=in1, op=op), reads=r, writes=w)

    def ts(self, eng, out, in0, s1, s2, op0, op1, r, w):
        if op1 is None:
            self.P.op(eng, lambda e: e.tensor_scalar(out=out, in0=in0, scalar1=s1, scalar2=None, op0=op0), reads=r, writes=w)
        else:
            self.P.op(eng, lambda e: e.tensor_scalar(out=out, in0=in0, scalar1=s1, scalar2=s2, op0=op0, op1=op1), reads=r, writes=w)

    def stt(self, out, in0, scalar, in1, op0, op1, r, w):
        self.P.op('dve', lambda e: e.scalar_tensor_tensor(out=out, in0=in0, scalar=scalar, in1=in1, op0=op0, op1=op1),
                  reads=r, writes=w)

    def cp(self, eng, out, in_, r, w):
        if eng == 'act':
            self.P.op('act', lambda e: e.copy(out=out, in_=in_), reads=r, writes=w)
        else:
            self.P.op(eng, lambda e: e.tensor_copy(out=out, in_=in_), reads=r, writes=w)

    def recip(self, out, in_, r, w):
        self.P.op('dve', lambda e: e.reciprocal(out=out, in_=in_), reads=r, writes=w)

    def memset(self, eng, ap, val, w):
        self.P.op(eng, lambda e: e.memset(ap, val), reads=[], writes=w)

    def dma(self, q, out, in_, r=(), w=(), final=False, **kw):
        self.P.dma(q, out, in_, reads=r, writes=w, final=final, **kw)

    def consts(self, ident_d):
        self.identf = self.sb("identf", [128, 128], F32)
        self.identb = self.sb("identb", [128, 128], BF16)
        self.dma('sp', self.identf[:], ident_d, w=['ident'])
        self.cp('dve', self.identb[:], self.identf[:], ['ident'], ['ident'])

    def gain_cols(self, name, g_d):
        t = self.sb(name, [128, KC], F32)
        self.dma('sp', t[:], g_d.rearrange("(kc p) -> p kc", p=128), w=[name], allow_slow_non_contiguous=True)
        return t

    def bcast_row(self, name, vec_d, n):
        t = self.sb(name, [128, n], F32)
        self.dma('sp', t[:], vec_d.partition_broadcast(128), w=[name])
        return t

    def load_weight(self, name, w_d, kchunks, ncols, gcol=None, gkey=None, stage_cols=2048, q='sp'):
        wb = self.sb(name, [128, kchunks, ncols], BF16)
        if not hasattr(self, 'wstage'):
            self.wstage = [self.sb(f"wstage{i}", [128, stage_cols], F32) for i in range(2)]
            self.wstage_n = 0
            self.wstage_cols = stage_cols
        sc = self.wstage_cols
        wv = w_d.rearrange("(kc p) n -> p kc n", p=128)
        for kc in range(kchunks):
            for c0 in range(0, ncols, sc):
                cw = min(sc, ncols - c0)
                b = self.wstage_n % 2
                self.wstage_n += 1
                stg = self.wstage[b]
                self.dma(q, stg[:, 0:cw], wv[:, kc, c0:c0 + cw], w=[f'wstage{b}'])
                eng = 'act' if (kc % 2 == 0) else 'dve'
                if gcol is not None:
                    if eng == 'act':
                        self.act(wb[:, kc, c0:c0 + cw], stg[:, 0:cw], AF.Copy, [f'wstage{b}', gkey], [f'{name}{kc}'],
                                 scale=gcol[:, kc:kc + 1])
                    else:
                        self.ts('dve', wb[:, kc, c0:c0 + cw], stg[:, 0:cw], gcol[:, kc:kc + 1], None, ALU.mult, None,
                                [f'wstage{b}', gkey], [f'{name}{kc}'])
                else:
                    self.cp(eng, wb[:, kc, c0:c0 + cw], stg[:, 0:cw], [f'wstage{b}'], [f'{name}{kc}'])
        return wb

    def rstd_of(self, x_ap, xkey, ss, rstd, junk, key, ncols=D):
        self.act(junk, x_ap, AF.Square, [xkey], ['junk', key + 'ss'], accum_out=ss)
        self.ts('dve', rstd, ss, 1.0 / ncols, EPS, ALU.mult, ALU.add, [key + 'ss'], [key])
        self.act(rstd, rstd, AF.Sqrt, [key], [key])
        self.recip(rstd, rstd, [key], [key])


def pipeline(make_gen, n):
    active = []
    for i in range(n):
        for g in list(active):
            try:
                next(g)
            except StopIteration:
                active.remove(g)
        g = make_gen(i)
        active.append(g)
        try:
            next(g)
        except StopIteration:
            active.remove(g)
    while active:
        for g in list(active):
            try:
                next(g)
            except StopIteration:
                active.remove(g)


def pipeline_gen(make_gen, n):
    active = []
    for i in range(n):
        for g in list(active):
            try:
                next(g)
            except StopIteration:
                active.remove(g)
        g = make_gen(i)
        active.append(g)
        try:
            next(g)
        except StopIteration:
            active.remove(g)
        yield
    while active:
        for g in list(active):
            try:
                next(g)
            except StopIteration:
                active.remove(g)
        yield


def run_streams(k, streams):
    base_pfx = k.pfx
    gens = []
    for (pf, io, gf) in streams:
        gens.append([pf, io, None, gf])
    active = list(gens)
    while active:
        for st in list(active):
            pf, io, g, gf = st
            k.pfx = base_pfx + pf
            k.P.key_prefix = pf
            k.P.ps_prefix = pf
            k.io = io
            try:
                if g is None:
                    st[2] = gf(k)
                    g = st[2]
                next(g)
            except StopIteration:
                active.remove(st)
    k.pfx = base_pfx
    k.P.key_prefix = ''
    k.P.ps_prefix = ''


GELU_C = 1.5957691216057308


def norm_T(k, xt, xkey, xn, xnkey, xT_dst, xTkey, psT, psTkey, ss, rstd, junk, key, evac_eng='act'):
    k.rstd_of(xt, xkey, ss, rstd, junk, key)
    k.ts('dve', xn, xt, rstd, None, ALU.mult, None, [xkey, key], [xnkey])
    for kc in range(KC):
        k.tr(psT[:, kc * 128:(kc + 1) * 128], xn[:, kc * 128:(kc + 1) * 128], k.identb[:], [xnkey], [psTkey])
    k.cp(evac_eng, xT_dst, psT[:].rearrange("p (k t) -> p k t", k=KC), [psTkey], [xTkey])


def post_norm_res(k, ps2, pskeys, ht, hkey, gbc, gkey, tmp2, tmpkeys, ss2, rstd, junk, key):
    for j in range(2):
        k.act(junk[:, 0:512], ps2[j], AF.Square, [pskeys[j]], ['junk', key + f'ss{j}'], accum_out=ss2[:, j:j + 1])
    k.tt('dve', ss2[:, 0:1], ss2[:, 0:1], ss2[:, 1:2], ALU.add, [key + 'ss0', key + 'ss1'], [key + 'ss0'])
    k.ts('dve', rstd, ss2[:, 0:1], 1.0 / D, EPS, ALU.mult, ALU.add, [key + 'ss0'], [key])
    k.act(rstd, rstd, AF.Sqrt, [key], [key])
    k.recip(rstd, rstd, [key], [key])
    for j in range(2):
        sl = slice(j * 512, (j + 1) * 512)
        k.stt(tmp2[j], ps2[j], rstd, gbc[:, sl], ALU.mult, ALU.mult, [pskeys[j], key, gkey], [tmpkeys[j]])
        k.tt('pool', ht[:, sl], ht[:, sl], tmp2[j], ALU.add, [tmpkeys[j], hkey], [hkey])


def build_C1(NTOK, glu, k=None, ob_fm=False):
    k = k or K()
    NT = NTOK // 128
    oa = k.din("oa", [NTOK, 512])
    if ob_fm:
        obT = k.din("obT", [512, NTOK])
    else:
        ob = k.din("ob", [NTOK, 512])
    hin = k.din("hin", [NTOK, D])
    wout = k.din("wout", [D, D])
    g1 = k.din("g1", [D])
    ident_d = k.din("ident", [128, 128])
    if glu:
        wglu = k.din("wglu", [512, 512])
        bglu = k.din("bglu", [512])
    hout = k.dout("hout", [NTOK, D])
    k.consts(ident_d)
    g1bc = k.bcast_row("g1bc", g1, D)
    Wout = k.load_weight("Wout", wout, KC, D, stage_cols=1024)
    if glu:
        Wglu = k.load_weight("Wglu", wglu, 4, 512)
        bgbc = k.bcast_row("bgbc", bglu, 512)

    def ring(nm, shape, n, dt=F32):
        return [k.sb(f"{nm}{j}", shape, dt) for j in range(n)]
    oc = ring("oc", [128, D], 10 if glu else 4)
    ocb = ring("ocb", [128, D], 3, BF16)
    oT = ring("oT", [128, KC, 128], 3, BF16)
    ht = ring("ht", [128, D], 4)
    mix = ring("mix", [128, D], 5)
    tmp = ring("tmp", [128, D], 3)
    ss2 = ring("ss2", [128, 2], 4)
    rstd = ring("rstd", [128, 1], 5)
    junk = k.sb("junk", [128, D], BF16)
    if ob_fm:
        obt = ring("obt", [128, 4, 128], 4)
    if glu:
        yb = ring("yb", [128, 512], 3, BF16)
        yT = ring("yT", [128, 4, 128], 3, BF16)
        t1 = ring("t1", [128, 512], 9)
        zs = ring("zs", [128, 512], 4)
        psTg = k.ps("psTg", [128, D], BF16)
        psG = k.ps("psG", [128, 512])
    psTm = [k.ps(f"psTm{j}", [128, D], BF16) for j in range(2)]
    psM = [k.ps(f"psM{j}", [128, 512]) for j in range(4)]

    def tile(i):
        rows = slice(i * 128, (i + 1) * 128)
        def T(lst, nm):
            j = i % len(lst)
            return lst[j], f'{nm}{j}'
        oc_, koc = T(oc, 'oc'); ocb_, kocb = T(ocb, 'ocb'); oT_, koT = T(oT, 'oT'); ht_, kht = T(ht, 'ht')
        mix_, kmix = T(mix, 'mix'); tmp_, ktmp = T(tmp, 'tmp'); ss_, kss = T(ss2, 'ss2'); rs_, krs = T(rstd, 'rstd')
        pm = [psM[2 * (i % 2)], psM[2 * (i % 2) + 1]]
        kpm = [f'psM{2 * (i % 2)}', f'psM{2 * (i % 2) + 1}']
        ptm, kptm = psTm[i % 2], f'psTm{i % 2}'
        kA, kB = koc + 'A', koc + 'B'
        k.dma('sp', oc_[:, 0:512], oa[rows, :], w=[kA])
        if ob_fm:
            obt_, kobt = T(obt, 'obt')
            k.dma('sp', obt_[:], obT[:, rows].rearrange("(a p) t -> p a t", p=128), w=[kobt])
        else:
            k.dma('sp', oc_[:, 512:1024], ob[rows, :], w=[kB])
        yield
        if glu:
            y = oc_[:, 512:1024]
            yb_, kyb = T(yb, 'yb'); yT_, kyT = T(yT, 'yT'); t1_, kt1 = T(t1, 't1'); zs_, kzs = T(zs, 'zs')
            k.cp('dve', yb_[:], y, [kB], [kyb])
            k.act(t1_[:], y, AF.Square, [kB], [kt1])
            k.act(t1_[:], t1_[:], AF.Copy, [kt1], [kt1], scale=0.044715, bias=1.0)
            yield
            for kc in range(4):
                k.tr(psTg[:, kc * 128:(kc + 1) * 128], yb_[:, kc * 128:(kc + 1) * 128], k.identb[:], [kyb], ['psTg'])
            k.tt('pool', t1_[:], t1_[:], y, ALU.mult, [kt1, kB], [kt1])
            yield
            k.cp('act', yT_[:], psTg[:, 0:512].rearrange("p (k t) -> p k t", k=4), ['psTg'], [kyT])
            k.act(t1_[:], t1_[:], AF.Sigmoid, [kt1], [kt1], scale=GELU_C)
            yield
            for kc in range(4):
                k.mm(psG[:], yT_[:, kc, :], Wglu[:, kc, :], kc == 0, kc == 3, [kyT, f'Wglu{kc}'], ['psG'])
            yield
            k.tt('dve', zs_[:], psG[:], bgbc[:], ALU.add, ['psG', 'bgbc'], [kzs])
            yield
            k.act(zs_[:], zs_[:], AF.Sigmoid, [kzs], [kzs])
            yield
            k.tt('dve', zs_[:], t1_[:], zs_[:], ALU.mult, [kt1, kzs], [kzs])
            k.tt('dve', y, y, zs_[:], ALU.mult, [kB, kzs], [kB])
        if ob_fm:
            k.cp('dve', ocb_[:, 0:512], oc_[:, 0:512], [kA], [kocb])
            k.cp('pool', oT_[:, 4:8, :], obt_[:], [kobt], [koT + 'b'])
        else:
            k.cp('dve', ocb_[:], oc_[:], [kA, kB], [kocb])
        yield
        nk = 4 if ob_fm else KC
        for kc in range(nk):
            k.tr(ptm[:, kc * 128:(kc + 1) * 128], ocb_[:, kc * 128:(kc + 1) * 128], k.identb[:], [kocb], [kptm])
        yield
        k.cp('act', oT_[:, 0:nk, :], ptm[:, 0:nk * 128].rearrange("p (k t) -> p k t", k=nk), [kptm], [koT])
        yield
        for cg in range(2):
            for kc in range(KC):
                ok_ = (koT + 'b') if (ob_fm and kc >= 4) else koT
                k.mm(pm[cg][:], oT_[:, kc, :], Wout[:, kc, cg * 512:(cg + 1) * 512], kc == 0, kc == KC - 1,
                     [ok_, f'Wout{kc}'], [kpm[cg]])
        yield
        for j in range(2):
            k.act(junk[:, 0:512], pm[j][:], AF.Square, [kpm[j]], ['junk', kss], accum_out=ss_[:, j:j + 1])
        for j in range(2):
            k.cp('act', mix_[:, j * 512:(j + 1) * 512], pm[j][:], [kpm[j]], [kmix])
        k.dma('sp', ht_[:], hin[rows, :], w=[kht])
        yield
        k.tt('dve', ss_[:, 0:1], ss_[:, 0:1], ss_[:, 1:2], ALU.add, [kss], [kss])
        k.ts('dve', rs_[:], ss_[:, 0:1], 1.0 / D, EPS, ALU.mult, ALU.add, [kss], [krs])
        yield
        k.act(rs_[:], rs_[:], AF.Sqrt, [krs], [krs])
        yield
        k.recip(rs_[:], rs_[:], [krs], [krs])
        k.stt(tmp_[:], mix_[:], rs_[:], g1bc[:], ALU.mult, ALU.mult, [kmix, krs, 'g1bc'], [ktmp])
        yield
        k.tt('pool', ht_[:], ht_[:], tmp_[:], ALU.add, [kht, ktmp], [kht])
        k.dma('pool', hout[rows, :], ht_[:], r=[kht], final=True)

    pipeline(tile, NT)
    return k.finish()


def build_C3(NTOK, k=None):
    k = k or K()
    NB = NTOK // 512
    DFF = 4096
    FC = DFF // 128
    hin = k.din("hin", [NTOK, D])
    w1 = k.din("w1", [D, DFF])
    w2 = k.din("w2", [DFF, D])
    g4 = k.din("g4", [D])
    g5 = k.din("g5", [D])
    ident_d = k.din("ident", [128, 128])
    hout = k.dout("hout", [NTOK, D])
    k.consts(ident_d)
    g4c = k.gain_cols("g4c", g4)
    g5bc = k.bcast_row("g5bc", g5, D)
    W1 = k.load_weight("W1", w1, KC, DFF, gcol=g4c, gkey='g4c', stage_cols=512)
    W2 = k.load_weight("W2", w2, FC, D, stage_cols=512)
    ht = [k.sb(f"ht{i}", [128, D]) for i in range(4)]
    xn = [k.sb(f"xn{i}", [128, D], BF16) for i in range(2)]
    xT = k.sb("xT", [128, KC, 512], BF16)
    AT = k.sb("AT", [128, FC, 512], BF16)
    sq = [k.sb(f"sq{i}", [128, 512]) for i in range(2)]
    junk = k.sb("junk", [128, D], BF16)
    ss = [k.sb(f"ss{i}", [128, 1]) for i in range(2)]
    ss2 = [k.sb(f"ss2{i}", [128, 2]) for i in range(2)]
    rstd = [k.sb(f"rstd{i}", [128, 1]) for i in range(2)]
    rstd2 = [k.sb(f"rstdb{i}", [128, 1]) for i in range(2)]
    psT = k.ps("psT", [128, D], BF16)
    psU = [k.ps(f"psU{i}", [128, 512]) for i in range(3)]
    psD = [k.ps(f"psD{i}", [128, 512]) for i in range(4)]
    nu = 0
    for blk in range(NB):
        for tt in range(4):
            i = blk * 4 + tt
            b = i % 2
            rows = slice(i * 128, (i + 1) * 128)
            k.dma('sp', ht[tt][:], hin[rows, :], w=[f'ht{tt}'])
            norm_T(k, ht[tt][:], f'ht{tt}', xn[b][:], f'xn{b}', xT[:, :, tt * 128:(tt + 1) * 128], 'xT', psT[:], 'psT',
                   ss[b][:], rstd[b][:], junk[:], f'n{b}')
        for fc in range(FC):
            pu = nu % 3
            nu += 1
            for kc in range(KC):
                k.mm(psU[pu][:], W1[:, kc, fc * 128:(fc + 1) * 128], xT[:, kc, :], kc == 0, kc == KC - 1,
                     [f'W1{kc}', 'xT'], [f'psU{pu}'])
            sb_ = fc % 2
            k.act(sq[sb_][:], psU[pu][:], AF.Square, [f'psU{pu}'], [f'sq{sb_}'])
            k.stt(AT[:, fc, :], psU[pu][:], 0.0, sq[sb_][:], ALU.is_gt, ALU.mult, [f'psU{pu}', f'sq{sb_}'], ['AT'])
        for tt in range(4):
            i = blk * 4 + tt
            b = i % 2
            rows = slice(i * 128, (i + 1) * 128)
            for cg in range(2):
                pd = 2 * b + cg
                for fc in range(FC):
                    k.mm(psD[pd][:], AT[:, fc, tt * 128:(tt + 1) * 128], W2[:, fc, cg * 512:(cg + 1) * 512],
                         fc == 0, fc == FC - 1, ['AT', f'W2{fc}'], [f'psD{pd}'])
            post_norm_res(k, [psD[2 * b][:], psD[2 * b + 1][:]], [f'psD{2 * b}', f'psD{2 * b + 1}'], ht[tt], f'ht{tt}',
                          g5bc, 'g5bc', [sq[0][:], sq[1][:]], ['sq0', 'sq1'], ss2[b], rstd2[b][:], junk, f'pn{b}')
            k.dma('pool', hout[rows, :], ht[tt][:], r=[f'ht{tt}'], final=True)
    return k.finish()


def build_C2(NTOK, k=None):
    k = k or K()
    NB = NTOK // 512
    MEM = 256
    hin = k.din("hin", [NTOK, D])
    mem = k.din("mem", [MEM, D])
    wq = k.din("wq", [D, D])
    wk = k.din("wk", [D, D])
    wv = k.din("wv", [D, D])
    wo = k.din("wo", [D, D])
    g2 = k.din("g2", [D])
    g3 = k.din("g3", [D])
    g6 = k.din("g6", [D])
    ident_d = k.din("ident", [128, 128])
    hout = k.dout("hout", [NTOK, D])
    k.consts(ident_d)
    g2c = k.gain_cols("g2c", g2)
    g6c = k.gain_cols("g6c", g6)
    g3bc = k.bcast_row("g3bc", g3, D)
    Wk = k.load_weight("Wk", wk, KC, D, gcol=g6c, gkey='g6c', stage_cols=1024)
    Wv = k.load_weight("Wv", wv, KC, D, gcol=g6c, gkey='g6c', stage_cols=1024)
    Wq = k.load_weight("Wq", wq, KC, D, gcol=g2c, gkey='g2c', stage_cols=1024)
    Wo = k.load_weight("Wo", wo, KC, D, stage_cols=1024)
    ht = [k.sb(f"ht{i}", [128, D]) for i in range(2)]
    xn = [k.sb(f"xn{i}", [128, D], BF16) for i in range(2)]
    xT = [k.sb(f"xT{i}", [128, KC, 512], BF16) for i in range(2)]
    memT = k.sb("memT", [128, KC, MEM], BF16)
    KT = k.sb("KT", [128, KC, MEM], BF16)
    V = k.sb("V", [128, 2, D], BF16)
    QT = [k.sb(f"QT{i}", [128, KC, 512], BF16) for i in range(2)]
    Pm = [k.sb(f"Pm{i}", [128, 4, MEM], BF16) for i in range(3)]
    Pn = [k.sb(f"Pn{i}", [128, 4, MEM], BF16) for i in range(3)]
    PT = [k.sb(f"PT{i}", [128, 8, 128], BF16) for i in range(3)]
    OT = [k.sb(f"OT{i}", [128, KC, 128], BF16) for i in range(3)]
    tmp = [k.sb(f"tmp{i}", [128, 512]) for i in range(2)]
    junk = k.sb("junk", [128, D], BF16)
    ss = [k.sb(f"ss{i}", [128, 1]) for i in range(2)]
    ss2 = [k.sb(f"ss2{i}", [128, 2]) for i in range(2)]
    rstd = [k.sb(f"rstd{i}", [128, 1]) for i in range(2)]
    rstd2 = [k.sb(f"rstdb{i}", [128, 1]) for i in range(2)]
    mx = [k.sb(f"mx{i}", [128, 4]) for i in range(3)]
    sm = [k.sb(f"sm{i}", [128, 4]) for i in range(3)]
    psT = k.ps("psT", [128, D], BF16)
    psA = k.ps("psA", [128, 1024])
    psS = k.ps("psS", [128, 1024])
    psX = k.ps("psX", [128, 1024])
    for mt in range(2):
        k.dma('sp', ht[mt][:], mem[mt * 128:(mt + 1) * 128, :], w=[f'ht{mt}'])
        norm_T(k, ht[mt][:], f'ht{mt}', xn[mt][:], f'xn{mt}', memT[:, :, mt * 128:(mt + 1) * 128], 'memT', psT[:], 'psT',
               ss[mt][:], rstd[mt][:], junk[:], f'n{mt}')
    for cc in range(KC):
        pa = cc % 2
        for kc in range(KC):
            k.mm(psA[:, pa * 512:pa * 512 + MEM], Wk[:, kc, cc * 128:(cc + 1) * 128], memT[:, kc, :], kc == 0, kc == KC - 1,
                 [f'Wk{kc}', 'memT'], [f'psA{pa}'])
        k.cp('act' if cc % 2 else 'dve', KT[:, cc, :], psA[:, pa * 512:pa * 512 + MEM], [f'psA{pa}'], [f'KT{cc}'])
    for mt in range(2):
        for cg in range(2):
            for kc in range(KC):
                k.mm(psX[:, cg * 512:(cg + 1) * 512], memT[:, kc, mt * 128:(mt + 1) * 128], Wv[:, kc, cg * 512:(cg + 1) * 512],
                     kc == 0, kc == KC - 1, ['memT', f'Wv{kc}'], [f'psX{cg}'])
            k.cp('act' if cg else 'dve', V[:, mt, cg * 512:(cg + 1) * 512], psX[:, cg * 512:(cg + 1) * 512], [f'psX{cg}'], [f'V{mt}{cg}'])
    xt6 = [k.sb(f"xt6_{i}", [128, D]) for i in range(6)]
    ss6 = [k.sb(f"ss6_{i}", [128, 1]) for i in range(4)]
    rs6 = [k.sb(f"rs6_{i}", [128, 1]) for i in range(5)]
    xn3 = [k.sb(f"xn3_{i}", [128, D], BF16) for i in range(3)]
    psTx = k.ps("psTx", [128, D], BF16)

    def tile(i):
        blk, tt = divmod(i, 4)
        xb = blk % 2
        b = i % 3
        rows = slice(i * 128, (i + 1) * 128)
        tsl = slice(tt * 128, (tt + 1) * 128)
        def T(lst, nm):
            j = i % len(lst)
            return lst[j], f'{nm}{j}'
        xt_, kxt = T(xt6, 'xt6'); ss_, kss = T(ss6, 'ss6'); rs_, krs = T(rs6, 'rs6'); xn_, kxn = T(xn3, 'xn3')
        hb = i % 2
        k.dma('sp', xt_[:], hin[rows, :], w=[kxt])
        yield
        k.act(junk[:], xt_[:], AF.Square, [kxt], ['junk', kss], accum_out=ss_[:])
        yield
        k.ts('dve', rs_[:], ss_[:], 1.0 / D, EPS, ALU.mult, ALU.add, [kss], [krs])
        yield
        k.act(rs_[:], rs_[:], AF.Sqrt, [krs], [krs])
        yield
        k.recip(rs_[:], rs_[:], [krs], [krs])
        k.ts('dve', xn_[:], xt_[:], rs_[:], None, ALU.mult, None, [kxt, krs], [kxn])
        yield
        for kc in range(KC):
            k.tr(psTx[:, kc * 128:(kc + 1) * 128], xn_[:, kc * 128:(kc + 1) * 128], k.identb[:], [kxn], ['psTx'])
        yield
        k.cp('act', xT[xb][:, :, tsl], psTx[:].rearrange("p (k t) -> p k t", k=KC), ['psTx'], [f'xT{xb}'])
        yield
        if tt == 3:
            for cc in range(KC):
                pa = cc % 2
                for kc in range(KC):
                    k.mm(psA[:, pa * 512:(pa + 1) * 512], Wq[:, kc, cc * 128:(cc + 1) * 128], xT[xb][:, kc, :], kc == 0, kc == KC - 1,
                         [f'Wq{kc}', f'xT{xb}'], [f'psA{pa}'])
                k.cp('act' if cc % 2 else 'dve', QT[xb][:, cc, :], psA[:, pa * 512:(pa + 1) * 512], [f'psA{pa}'], [f'QT{xb}{cc}'])
        yield
        yield
        yield
        yield
        for h in range(4):
            sb_ = h // 2
            for j in range(2):
                cc = 2 * h + j
                k.mm(psS[:, h * MEM:(h + 1) * MEM], QT[xb][:, cc, tsl], KT[:, cc, :], j == 0, j == 1,
                     [f'QT{xb}{cc}', f'KT{cc}'], [f'psS{sb_}'])
        k.P.op('dve', lambda e, b=b: e.tensor_reduce(out=mx[b][:], in_=psS[:].rearrange("p (h m) -> p h m", h=4),
                                                    axis=AX.X, op=ALU.max),
               reads=['psS0', 'psS1'], writes=[f'mx{b}'])
        k.ts('dve', mx[b][:], mx[b][:], -1.0 / 16.0, None, ALU.mult, None, [f'mx{b}'], [f'mx{b}'])
        for h in range(4):
            k.act(Pm[b][:, h, :], psS[:, h * MEM:(h + 1) * MEM], AF.Exp, [f'psS{h // 2}', f'mx{b}'], [f'Pm{b}', f'sm{b}'],
                  scale=1.0 / 16.0, bias=mx[b][:, h:h + 1], accum_out=sm[b][:, h:h + 1])
        k.recip(sm[b][:], sm[b][:], [f'sm{b}'], [f'sm{b}'])
        k.tt('dve', Pn[b][:], Pm[b][:], sm[b][:].unsqueeze(2).broadcast_to([128, 4, MEM]), ALU.mult,
             [f'Pm{b}', f'sm{b}'], [f'Pn{b}'])
        yield
        for h in range(4):
            for mt in range(2):
                k.tr(psT[:, (h * 2 + mt) * 128:(h * 2 + mt + 1) * 128], Pn[b][:, h, mt * 128:(mt + 1) * 128], k.identb[:],
                     [f'Pn{b}'], ['psT'])
        k.cp('act', PT[b][:], psT[:].rearrange("p (k t) -> p k t", k=8), ['psT'], [f'PT{b}'])
        for cc in range(KC):
            h = cc // 2
            pa = cc // 4
            for mt in range(2):
                k.mm(psA[:, cc * 128:(cc + 1) * 128], V[:, mt, cc * 128:(cc + 1) * 128], PT[b][:, h * 2 + mt, :],
                     mt == 0, mt == 1, [f'V{mt}{cc // 4}', f'PT{b}'], [f'psA{pa}'])
        k.cp('dve', OT[b][:, 0:4, :], psA[:, 0:512].rearrange("p (k t) -> p k t", k=4), ['psA0'], [f'OT{b}_0'])
        k.cp('act', OT[b][:, 4:8, :], psA[:, 512:1024].rearrange("p (k t) -> p k t", k=4), ['psA1'], [f'OT{b}_1'])
        k.dma('sp', ht[hb][:], hin[rows, :], w=[f'ht{hb}'])
        yield
        for cg in range(2):
            for cc in range(KC):
                k.mm(psX[:, cg * 512:(cg + 1) * 512], OT[b][:, cc, :], Wo[:, cc, cg * 512:(cg + 1) * 512],
                     cc == 0, cc == KC - 1, [f'OT{b}_{cc // 4}', f'Wo{cc}'], [f'psX{cg}'])
        post_norm_res(k, [psX[:, 0:512], psX[:, 512:1024]], ['psX0', 'psX1'], ht[hb], f'ht{hb}',
                      g3bc, 'g3bc', [tmp[0][:], tmp[1][:]], ['tmp0', 'tmp1'], ss2[b % 2], rstd2[b % 2][:], junk, f'pn{b % 2}')
        k.dma('pool', hout[rows, :], ht[hb][:], r=[f'ht{hb}'], final=True)

    pipeline(tile, NTOK // 128)
    return k.finish()


def build_A2(NTOK, NC, fm, NF, k=None):
    k = k or K()
    NB = NTOK // 512
    x = k.din("x", [NTOK, D])
    gain = k.din("gain", [D])
    W = k.din("W", [D, NC])
    ident_d = k.din("ident", [128, 128])
    out = k.dout("out", [NTOK, NC])
    outT = k.dout("outT", [NF, NTOK])
    k.consts(ident_d)
    gc = k.gain_cols("gc", gain)
    Wb = k.load_weight("Wb", W, KC, NC, gcol=gc, gkey='gc', stage_cols=1408)
    cgs = [(c0, min(512, NC - c0)) for c0 in range(0, NC, 512)]
    def ring(nm, shape, n, dt=F32):
        return [k.sb(f"{nm}{j}", shape, dt) for j in range(n)]
    xt = ring("xt", [128, D], 6)
    xn = ring("xn", [128, D], 3, BF16)
    xT = [k.sb(f"xT{i}", [128, KC, 512], BF16) for i in range(2)]
    ot = [k.sb(f"ot{i}", [128, NC]) for i in range(2)]
    ft = [k.sb(f"ft{i}", [128, 512]) for i in range(2)]
    junk = k.sb("junk", [128, D], BF16)
    ss = ring("ss", [128, 1], 4)
    rstd = ring("rstd", [128, 1], 5)
    psT = k.ps("psT", [128, D], BF16)
    psO = [k.ps(f"psO{i}", [128, 512]) for i in range(4)]
    psF = [k.ps(f"psF{i}", [128, 512]) for i in range(2)]
    cnt = {'no': 0, 'nf': 0}

    def tile(i):
        blk, tt = divmod(i, 4)
        xb = blk % 2
        def T(lst, nm):
            j = i % len(lst)
            return lst[j], f'{nm}{j}'
        xt_, kxt = T(xt, 'xt'); xn_, kxn = T(xn, 'xn'); ss_, kss = T(ss, 'ss'); rs_, krs = T(rstd, 'rstd')
        k.dma('sp', xt_[:], x[i * 128:(i + 1) * 128, :], w=[kxt])
        yield
        k.act(junk[:], xt_[:], AF.Square, [kxt], ['junk', kss], accum_out=ss_[:])
        yield
        k.ts('dve', rs_[:], ss_[:], 1.0 / D, EPS, ALU.mult, ALU.add, [kss], [krs])
        yield
        k.act(rs_[:], rs_[:], AF.Sqrt, [krs], [krs])
        yield
        k.recip(rs_[:], rs_[:], [krs], [krs])
        k.ts('dve', xn_[:], xt_[:], rs_[:], None, ALU.mult, None, [kxt, krs], [kxn])
        yield
        for kc in range(KC):
            k.tr(psT[:, kc * 128:(kc + 1) * 128], xn_[:, kc * 128:(kc + 1) * 128], k.identb[:], [kxn], ['psT'])
        yield
        k.cp('act', xT[xb][:, :, tt * 128:(tt + 1) * 128], psT[:].rearrange("p (k t) -> p k t", k=KC), ['psT'], [f'xT{xb}'])
        yield
        if tt != 3:
            return
        for t2 in range(4):
            i2 = blk * 4 + t2
            b = i2 % 2
            for ci, (c0, cw) in enumerate(cgs):
                pb = cnt['no'] % 4
                cnt['no'] += 1
                for kc in range(KC):
                    k.mm(psO[pb][:, 0:cw], xT[xb][:, kc, t2 * 128:(t2 + 1) * 128], Wb[:, kc, c0:c0 + cw], kc == 0, kc == KC - 1,
                         [f'xT{xb}', f'Wb{kc}'], [f'psO{pb}'])
                k.cp('dve' if pb % 2 == 0 else 'act', ot[b][:, c0:c0 + cw], psO[pb][:, 0:cw], [f'psO{pb}'], [f'ot{b}_{pb % 2}'])
            k.dma('pool', out[i2 * 128:(i2 + 1) * 128, :], ot[b][:], r=[f'ot{b}_0', f'ot{b}_1'], final=True)
        for (c0, cw, r0) in fm:
            pf = cnt['nf'] % 2
            cnt['nf'] += 1
            for kc in range(KC):
                k.mm(psF[pf][0:cw, :], Wb[:, kc, c0:c0 + cw], xT[xb][:, kc, :], kc == 0, kc == KC - 1,
                     [f'Wb{kc}', f'xT{xb}'], [f'psF{pf}'])
            k.cp('dve' if pf == 0 else 'act', ft[pf][0:cw, :], psF[pf][0:cw, :], [f'psF{pf}'], [f'ft{pf}'])
            k.dma('pool', outT[r0:r0 + cw, blk * 512:(blk + 1) * 512], ft[pf][0:cw, :], r=[f'ft{pf}'], final=True)

    pipeline(tile, NTOK // 128)
    return k.finish()


def gen_GLA(L, k):
    NT = L // 128
    qT = k.din("qT", [128, L])
    kT = k.din("kT", [128, L])
    ktok = k.din("ktok", [L, 128])
    v = k.din("v", [L, 256])
    gate = k.din("gate", [L, 256])
    dlrT = k.din("dlrT", [16, L])
    w2 = k.din("w2", [16, 128])
    bdec = k.din("bdec", [1, 128])
    gn = k.din("gn", [256])
    triu_d = k.din("triu", [128, 128])
    trigt_d = k.din("trigt", [128, 128])
    oa = k.dout("oa", [L, 256])

    triu = k.sb("triu_s", [128, 128])
    trigt = k.sb("trigt_s", [128, 128])
    k.dma('sp', triu[:], triu_d, w=['triu'])
    k.dma('sp', trigt[:], trigt_d, w=['trigt'])
    w2s = k.sb("w2s", [16, 128])
    k.dma('sp', w2s[:], w2, w=['w2s'])
    bds = k.sb("bds", [1, 128])
    k.dma('sp', bds[:], bdec, w=['bds'])
    ones1 = k.sb("ones1", [1, 128])
    k.memset('dve', ones1[:], 1.0, ['ones1'])
    gnbc = k.bcast_row("gnbc", gn, 256)
    S = k.sb("S", [128, 128], mybir.dt.float32r)
    zS = k.sb("zS", [128, 128])
    k.memset('dve', zS[:], 0.0, ['zS'])
    k.cp('dve', S[:], zS[:], ['zS'], ['S'])
    rm = k.sb("rm", [128, 2])
    k.memset('dve', rm[:], 0.0, ['rm'])
    k.memset('dve', rm[0:64, 0:1], 0.125, ['rm'])
    k.memset('dve', rm[64:128, 1:2], 0.125, ['rm'])

    def ring(nm, shape, n, dt=F32):
        return [k.sb(f"{nm}{j}", shape, dt) for j in range(n)]
    FR_ = mybir.dt.float32r
    triur = k.sb("triur", [128, 128], FR_)
    trigtr = k.sb("trigtr", [128, 128], FR_)
    k.cp('dve', triur[:], triu[:], ['triu'], ['triur'])
    k.cp('dve', trigtr[:], trigt[:], ['trigt'], ['trigtr'])
    vr = ring("vr", [128, 256], 10, FR_)
    qTt, kTt, kt, gt = ring("qTt", [128, 128], 8), ring("kTt", [128, 128], 8), ring("kt", [128, 128], 8), ring("gt", [128, 256], 8)
    vt = ring("vt", [128, 256], 11)
    dt_ = ring("dt", [16, 128], 3)
    la = ring("la", [128, 128], 4, mybir.dt.float32r)
    sg = ring("sg", [128, 256], 16)
    EqT, EkT, Eks = ring("EqT", [128, 128], 7), ring("EkT", [128, 128], 3), ring("Eks", [128, 128], 3)
    qin, kin, kst = ring("qin", [128, 2, 128], 5, mybir.dt.float32r), ring("kin", [128, 128], 3, mybir.dt.float32r), ring("kst", [128, 128], 5, mybir.dt.float32r)
    sc0, sc1 = ring("sc0_", [128, 128], 3, mybir.dt.float32r), ring("sc1_", [128, 128], 3, mybir.dt.float32r)
    osr = ring("osr", [128, 256], 6)
    osb = ring("osb", [128, 256], 3)
    ss, rs = ring("ss", [128, 2], 4), ring("rs", [128, 2], 5)
    ot = ring("ot", [128, 256], 3)
    junk = k.sb("junk", [128, 128])
    psZ = [k.ps(f"psZ{j}", [128, 512]) for j in range(2)]
    psA = [k.ps(f"psA{j}", [128, 512]) for j in range(2)]
    psB = [k.ps(f"psB{j}", [128, 512]) for j in range(2)]
    psC = [k.ps(f"psC{j}", [128, 512]) for j in range(2)]

    def tile(i):
        rows = slice(i * 128, (i + 1) * 128)
        R = lambda lst: (lst[i % len(lst)], f'{lst[0].name if hasattr(lst[0], "name") else id(lst)}_{i % len(lst)}')
        def T(lst, nm):
            j = i % len(lst)
            return lst[j], f'{nm}{j}'
        q_, kq = T(qTt, 'qTt'); kT_, kkT = T(kTt, 'kTt'); kt_, kkt = T(kt, 'kt'); v_, kv = T(vt, 'vt'); g_, kg = T(gt, 'gt')
        d_, kd = T(dt_, 'dt'); la_, kla = T(la, 'la'); sg_, ksg = T(sg, 'sg')
        Eq, kEq = T(EqT, 'EqT'); Ek, kEk = T(EkT, 'EkT'); Es, kEs = T(Eks, 'Eks')
        qi, kqi = T(qin, 'qin'); ki, kki = T(kin, 'kin'); ks, kks = T(kst, 'kst')
        scs = [T(sc0, 'sc0_'), T(sc1, 'sc1_')]
        orw, korw = T(osr, 'osr'); ob_, kob = T(osb, 'osb'); ss_, kss = T(ss, 'ss'); rs_, krs = T(rs, 'rs'); ot_, kot = T(ot, 'ot')
        pz, kpz = psZ[i % 2], f'psZ{i % 2}'
        pa, kpa = psA[i % 2], f'psA{i % 2}'
        pb, kpb = psB[i % 2], f'psB{i % 2}'
        pc, kpc = psC[i % 2], f'psC{i % 2}'
        k.dma('sp', q_[:], qT[:, rows], w=[kq])
        k.dma('sp', kT_[:], kT[:, rows], w=[kkT])
        k.dma('sp', kt_[:], ktok[rows, :], w=[kkt])
        k.dma('sp', v_[:], v[rows, :], w=[kv])
        k.dma('sp', g_[:], gate[rows, :], w=[kg])
        k.dma('sp', d_[:], dlrT[:, rows], w=[kd])
        yield
        k.mm(pz[:, 0:128], d_[:], w2s[:], True, False, [kd, 'w2s'], [kpz])
        k.mm(pz[:, 0:128], ones1[:], bds[:], False, True, ['ones1', 'bds'], [kpz])
        yield
        k.act(la_[:], pz[:, 0:128], AF.Exp, [kpz], [kla], scale=-1.0)
        k.act(la_[:], la_[:].bitcast(F32), AF.Ln, [kla], [kla], bias=1.0)
        k.act(sg_[:], g_[:], AF.Exp, [kg], [ksg], scale=-1.0)
        vr_, kvr = T(vr, 'vr')
        k.cp('act', vr_[:], v_[:], [kv], [kvr])
        yield
        k.ts('dve', la_[:], la_[:].bitcast(F32), -1.0 / 16.0, None, ALU.mult, None, [kla], [kla])
        k.ts('dve', sg_[:], sg_[:], 1.0, None, ALU.add, None, [ksg], [ksg])
        k.recip(sg_[:], sg_[:], [ksg], [ksg])
        yield
        k.mm(pa[:, 0:128], la_[:], triur[:], True, True, [kla, 'triur'], [kpa])
        k.mm(pa[:, 128:256], trigtr[:], la_[:], True, True, [kla, 'trigtr'], [kpa])
        yield
        k.act(Eq[:], pa[:, 0:128], AF.Exp, [kpa], [kEq])
        k.act(Ek[:], pa[:, 0:128], AF.Exp, [kpa], [kEk], scale=-1.0)
        k.act(Es[:], pa[:, 128:256], AF.Exp, [kpa], [kEs])
        yield
        for h in range(2):
            k.stt(qi[:, h, :], q_[:], rm[:, h:h + 1], Eq[:], ALU.mult, ALU.mult, [kq, kEq, 'rm'], [kqi])
        k.tt('pool', ki[:], kT_[:], Ek[:], ALU.mult, [kkT, kEk], [kki])
        k.tt('pool', ks[:], kt_[:], Es[:], ALU.mult, [kkt, kEs], [kks])
        k.tt('pool', sg_[:], sg_[:], g_[:], ALU.mult, [ksg, kg], [ksg])
        yield
        for h in range(2):
            hp = slice(h * 64, (h + 1) * 64)
            k.mm(pb[:, h * 128:(h + 1) * 128], ki[:], qi[:, h, :], True, True, [kki, kqi], [kpb])
        yield
        for h in range(2):
            k.tt('dve', scs[h][0][:], pb[:, h * 128:(h + 1) * 128], triu[:], ALU.mult, [kpb, 'triu'], [scs[h][1]])
        yield
        for h in range(2):
            hp = slice(h * 64, (h + 1) * 64)
            k.mm(pc[:, h * 128:(h + 1) * 128], scs[h][0][:], vr_[:, h * 128:(h + 1) * 128], True, False, [scs[h][1], kvr], [kpc])
            k.mm(pc[:, h * 128:(h + 1) * 128], qi[:, h, :], S[:], False, True, [kqi, 'S'], [kpc])
        k.mm(pc[:, 256:512], ks[:], vr_[:], True, True, [kks, kvr], [kpc])
        yield
        for h in range(2):
            hp = slice(h * 64, (h + 1) * 64)
            k.stt(S[hp, :], S[hp, :].bitcast(F32), Eq[hp, 127:128], pc[hp, 256 + h * 128:256 + (h + 1) * 128], ALU.mult, ALU.add,
                  ['S', kEq, kpc], ['S'])
        k.cp('act', orw[:], pc[:, 0:256], [kpc], [korw])
        yield
        for h in range(2):
            k.act(junk[:], orw[:, h * 128:(h + 1) * 128], AF.Square, [korw], ['junk', kss], accum_out=ss_[:, h:h + 1])
        yield
        k.ts('dve', rs_[:], ss_[:], 1.0 / 128.0, EPS, ALU.mult, ALU.add, [kss], [krs])
        yield
        k.act(rs_[:], rs_[:], AF.Ln, [krs], [krs])
        k.act(rs_[:], rs_[:], AF.Exp, [krs], [krs], scale=-0.5)
        yield
        for h in range(2):
            hs = slice(h * 128, (h + 1) * 128)
            k.stt(ob_[:, hs], orw[:, hs], rs_[:, h:h + 1], gnbc[:, hs], ALU.mult, ALU.mult, [korw, krs, 'gnbc'], [kob])
        yield
        k.tt('pool', ot_[:], ob_[:], sg_[:], ALU.mult, [kob, ksg], [kot])
        k.dma('pool', oa[rows, :], ot_[:], r=[kot], final=True)

    yield from pipeline_gen(tile, NT)


def build_GLA(L, k=None):
    k = k or K()
    for _ in gen_GLA(L, k):
        pass
    return k.finish()


TWO_PI = 2.0 * math.pi
C1 = 6.28125
C2 = TWO_PI - 6.28125
PI_LO = 3.1415925


def range_sincos(k, x, xkey, shape, s_out, c_out, skey, ckey, pfx):
    if not hasattr(k, 'rr_cache'):
        k.rr_cache = {}
    if pfx not in k.rr_cache:
        k.rr_cache[pfx] = (k.sb(pfx + "kf", shape), k.sb(pfx + "ki", shape, I32), k.sb(pfx + "r", shape), k.sb(pfx + "m", shape))
    kf, ki, r, m = k.rr_cache[pfx]
    a = lambda t: t[:]
    K1, K2, K3, K4 = pfx + 'kf', pfx + 'ki', pfx + 'r', pfx + 'm'
    k.ts('dve', a(kf), x, 1.0 / TWO_PI, None, ALU.mult, None, [xkey], [K1])
    k.cp('dve', a(ki), a(kf), [K1], [K2])
    k.cp('dve', a(kf), a(ki), [K2], [K1])
    k.stt(a(r), a(kf), -C1, x, ALU.mult, ALU.add, [K1, xkey], [K3])
    k.stt(a(r), a(kf), -C2, a(r), ALU.mult, ALU.add, [K1, K3], [K3])
    k.ts('dve', a(m), a(r), math.pi, -TWO_PI, ALU.is_gt, ALU.mult, [K3], [K4])
    k.tt('dve', a(r), a(r), a(m), ALU.add, [K3, K4], [K3])
    k.ts('dve', a(m), a(r), -math.pi, TWO_PI, ALU.is_lt, ALU.mult, [K3], [K4])
    k.tt('dve', a(r), a(r), a(m), ALU.add, [K3, K4], [K3])
    k.ts('dve', a(kf), a(r), PI_LO, -PI_LO, ALU.min, ALU.max, [K3], [K1])
    k.act(s_out, a(kf), AF.Sin, [K1], [skey])
    k.ts('dve', a(r), a(r), math.pi / 2, None, ALU.add, None, [K3], [K3])
    k.ts('dve', a(m), a(r), math.pi, -TWO_PI, ALU.is_gt, ALU.mult, [K3], [K4])
    k.tt('dve', a(r), a(r), a(m), ALU.add, [K3, K4], [K3])
    k.ts('dve', a(kf), a(r), PI_LO, -PI_LO, ALU.min, ALU.max, [K3], [K1])
    k.act(c_out, a(kf), AF.Sin, [K1], [ckey])


def gen_S5(L, k):
    NT = L // 128
    NS = 1024
    uT = k.din("uT", [256, L])
    u = k.din("u", [L, 256])
    lam_re = k.din("lam_re", [NS])
    lam_im = k.din("lam_im", [NS])
    lstep = k.din("lstep", [NS])
    Bre = k.din("Bre", [2, 128, 512])
    Bim = k.din("Bim", [2, 128, 512])
    Cre = k.din("Cre", [8, 128, 32])
    Cim = k.din("Cim", [8, 128, 32])
    dsk = k.din("dsk", [256])
    triu_d = k.din("triu", [128, 128])
    iop_d = k.din("iota_p", [128, 1])
    iof_d = k.din("iota_f", [128, 128])
    y = k.dout("y", [L, 256])

    k.push_scope([("triu_s", [128, 128], F32), ("dbc", [128, 256], F32), ("BBr", [128, 2, 512], mybir.dt.float32r), ("BBi", [128, 2, 512], mybir.dt.float32r),
                  ("Pr", [128, NS], F32), ("Pi", [128, NS], F32), ("Qr", [128, 8, 128], F32), ("Qi", [128, 8, 128], F32),
                  ("L128r", [128, 8], F32), ("L128i", [128, 8], F32), ("Cr", [128, 8, 32], F32), ("nCi", [128, 8, 32], F32),
                  ("car_r", [128, 8], F32), ("car_i", [128, 8], F32), ("ntriu", [128, 128], mybir.dt.float32r), ("nCr", [128, 8, 32], mybir.dt.float32r), ("triur", [128, 128], mybir.dt.float32r), ("Crr", [128, 8, 32], mybir.dt.float32r), ("nCir", [128, 8, 32], mybir.dt.float32r)])
    triu = k.sb("triu_s", [128, 128])
    k.dma('sp', triu[:], triu_d, w=['triu'])
    iop = k.sb("iop", [128, 1])
    k.dma('sp', iop[:], iop_d, w=['iop'])
    negp = k.sb("negp", [128, 1])
    k.ts('dve', negp[:], iop[:], -1.0, None, ALU.mult, None, ['iop'], ['negp'])
    iof = k.sb("iof", [128, 128])
    k.dma('sp', iof[:], iof_d, w=['iof'])
    dbc = k.bcast_row("dbc", dsk, 256)
    R = [128, NS]
    lr = k.bcast_row("lr", lam_re, NS)
    li = k.bcast_row("li", lam_im, NS)
    dl = k.bcast_row("dl", lstep, NS)
    k.ts('dve', lr[:], lr[:], -1e-4, None, ALU.min, None, ['lr'], ['lr'])
    k.act(dl[:], dl[:], AF.Exp, ['dl'], ['dl'])
    a_ = k.sb("a_", R)
    th = k.sb("th", R)
    k.tt('dve', a_[:], lr[:], dl[:], ALU.mult, ['lr', 'dl'], ['a_'])
    k.tt('dve', th[:], li[:], dl[:], ALU.mult, ['li', 'dl'], ['th'])
    sn = k.sb("sn", R)
    cs = k.sb("cs", R)
    range_sincos(k, th[:], 'th', R, sn[:], cs[:], 'sn', 'cs', 'rr_')
    ea = k.sb("ea", R)
    k.act(ea[:], a_[:], AF.Exp, ['a_'], ['ea'])
    nr = k.sb("nr", R)
    ni = k.sb("ni", R)
    k.tt('dve', nr[:], ea[:], cs[:], ALU.mult, ['ea', 'cs'], ['nr'])
    k.ts('dve', nr[:], nr[:], -1.0, None, ALU.add, None, ['nr'], ['nr'])
    k.tt('dve', ni[:], ea[:], sn[:], ALU.mult, ['ea', 'sn'], ['ni'])
    den = k.sb("den", R)
    t0 = k.sb("t0", R)
    k.tt('dve', den[:], lr[:], lr[:], ALU.mult, ['lr'], ['den'])
    k.tt('dve', t0[:], li[:], li[:], ALU.mult, ['li'], ['t0'])
    k.tt('dve', den[:], den[:], t0[:], ALU.add, ['den', 't0'], ['den'])
    k.recip(den[:], den[:], ['den'], ['den'])
    gr = k.sb("gr", R)
    gi = k.sb("gi", R)
    k.tt('dve', gr[:], nr[:], lr[:], ALU.mult, ['nr', 'lr'], ['gr'])
    k.tt('dve', t0[:], ni[:], li[:], ALU.mult, ['ni', 'li'], ['t0'])
    k.tt('dve', gr[:], gr[:], t0[:], ALU.add, ['gr', 't0'], ['gr'])
    k.tt('dve', gr[:], gr[:], den[:], ALU.mult, ['gr', 'den'], ['gr'])
    k.tt('dve', gi[:], ni[:], lr[:], ALU.mult, ['ni', 'lr'], ['gi'])
    k.tt('dve', t0[:], nr[:], li[:], ALU.mult, ['nr', 'li'], ['t0'])
    k.tt('dve', gi[:], gi[:], t0[:], ALU.subtract, ['gi', 't0'], ['gi'])
    k.tt('dve', gi[:], gi[:], den[:], ALU.mult, ['gi', 'den'], ['gi'])
    Br = k.sb("Br", [128, 2, 512])
    Bi = k.sb("Bi", [128, 2, 512])
    BBr = k.sb("BBr", [128, 2, 512])
    BBi = k.sb("BBi", [128, 2, 512])
    for hc in range(2):
        k.dma('sp', Br[:, hc, :], Bre[hc], w=[f'Br{hc}'])
        k.dma('sp', Bi[:, hc, :], Bim[hc], w=[f'Bi{hc}'])
    grv = gr[:].rearrange("p (h n) -> p h n", h=2)
    giv = gi[:].rearrange("p (h n) -> p h n", h=2)
    t0v = t0[:].rearrange("p (h n) -> p h n", h=2)
    BK = ['Br0', 'Br1', 'Bi0', 'Bi1']
    k.tt('dve', BBr[:], grv, Br[:], ALU.mult, ['gr'] + BK, ['BBr'])
    k.tt('dve', t0v, giv, Bi[:], ALU.mult, ['gi'] + BK, ['t0'])
    k.tt('dve', BBr[:], BBr[:].bitcast(F32), t0v, ALU.subtract, ['BBr', 't0'], ['BBr'])
    k.tt('dve', BBi[:], grv, Bi[:], ALU.mult, ['gr'] + BK, ['BBi'])
    k.tt('dve', t0v, giv, Br[:], ALU.mult, ['gi'] + BK, ['t0'])
    k.tt('dve', BBi[:], BBi[:].bitcast(F32), t0v, ALU.add, ['BBi', 't0'], ['BBi'])
    ang = k.sb("ang", R)
    k.ts('dve', ang[:], th[:], iop[:, 0:1], None, ALU.mult, None, ['th', 'iop'], ['ang'])
    Pr = k.sb("Pr", R)
    Pi = k.sb("Pi", R)
    range_sincos(k, ang[:], 'ang', R, sn[:], cs[:], 'sn', 'cs', 'rr_')
    k.act(ea[:], a_[:], AF.Exp, ['a_', 'negp'], ['ea'], scale=negp[:, 0:1])
    k.tt('dve', Pr[:], ea[:], cs[:], ALU.mult, ['ea', 'cs'], ['Pr'])
    k.stt(Pi[:], ea[:], -1.0, sn[:], ALU.mult, ALU.mult, ['ea', 'sn'], ['Pi'])
    Cs = [128, 8]
    lrc = k.sb("lrc", Cs)
    lic = k.sb("lic", Cs)
    dlc = k.sb("dlc", Cs)
    cv = lambda d: d.rearrange("(blk p) -> p blk", p=128)
    k.dma('sp', lrc[:], cv(lam_re), w=['lrc'], allow_slow_non_contiguous=True)
    k.dma('sp', lic[:], cv(lam_im), w=['lic'], allow_slow_non_contiguous=True)
    k.dma('sp', dlc[:], cv(lstep), w=['dlc'], allow_slow_non_contiguous=True)
    k.ts('dve', lrc[:], lrc[:], -1e-4, None, ALU.min, None, ['lrc'], ['lrc'])
    k.act(dlc[:], dlc[:], AF.Exp, ['dlc'], ['dlc'])
    ac = k.sb("ac", Cs)
    thc = k.sb("thc", Cs)
    k.tt('dve', ac[:], lrc[:], dlc[:], ALU.mult, ['lrc', 'dlc'], ['ac'])
    k.tt('dve', thc[:], lic[:], dlc[:], ALU.mult, ['lic', 'dlc'], ['thc'])
    Qr = k.sb("Qr", [128, 8, 128])
    Qi = k.sb("Qi", [128, 8, 128])
    angv = ang[:].rearrange("p (b t) -> p b t", b=8)
    eav = ea[:].rearrange("p (b t) -> p b t", b=8)
    for blk in range(8):
        k.ts('dve', angv[:, blk, :], iof[:], thc[:, blk:blk + 1], None, ALU.mult, None, ['iof', 'thc'], ['ang'])
    range_sincos(k, ang[:], 'ang', R, sn[:], cs[:], 'sn', 'cs', 'rr_')
    for blk in range(8):
        k.act(eav[:, blk, :], iof[:], AF.Exp, ['iof', 'ac'], ['ea'], scale=ac[:, blk:blk + 1])
    k.tt('dve', Qr[:].rearrange("p b t -> p (b t)"), ea[:], cs[:], ALU.mult, ['ea', 'cs'], ['Qr'])
    k.tt('dve', Qi[:].rearrange("p b t -> p (b t)"), ea[:], sn[:], ALU.mult, ['ea', 'sn'], ['Qi'])
    a128 = k.sb("a128", Cs)
    s128 = k.sb("s128", Cs)
    c128 = k.sb("c128", Cs)
    L128r = k.sb("L128r", Cs)
    L128i = k.sb("L128i", Cs)
    k.ts('dve', a128[:], thc[:], 128.0, None, ALU.mult, None, ['thc'], ['a128'])
    range_sincos(k, a128[:], 'a128', Cs, s128[:], c128[:], 's128', 'c128', 'rc_')
    k.act(a128[:], ac[:], AF.Exp, ['ac', 's128', 'c128'], ['a128'], scale=128.0)
    k.tt('dve', L128r[:], a128[:], c128[:], ALU.mult, ['a128', 'c128'], ['L128r'])
    k.tt('dve', L128i[:], a128[:], s128[:], ALU.mult, ['a128', 's128'], ['L128i'])
    Cr = k.sb("Cr", [128, 8, 32])
    nCi = k.sb("nCi", [128, 8, 32])
    k.dma('sp', Cr[:], Cre.rearrange("b p c -> p b c"), w=['Cr'])
    k.dma('sp', nCi[:], Cim.rearrange("b p c -> p b c"), w=['nCi'])
    k.ts('dve', nCi[:], nCi[:], -1.0, None, ALU.mult, None, ['nCi'], ['nCi'])
    car_r = k.sb("car_r", Cs)
    car_i = k.sb("car_i", Cs)
    k.memset('dve', car_r[:], 0.0, ['car_r0', 'car_r1'])
    k.memset('dve', car_i[:], 0.0, ['car_i0', 'car_i1'])
    ntriu = k.sb("ntriu", [128, 128])
    k.ts('dve', ntriu[:], triu[:], -1.0, None, ALU.mult, None, ['triu'], ['ntriu'])
    nCr = k.sb("nCr", [128, 8, 32])
    k.ts('dve', nCr[:], Cr[:], -1.0, None, ALU.mult, None, ['Cr'], ['nCr'])
    triur = k.sb("triur", [128, 128])
    k.cp('dve', triur[:], triu[:], ['triu'], ['triur'])
    Crr = k.sb("Crr", [128, 8, 32])
    k.cp('dve', Crr[:], Cr[:], ['Cr'], ['Crr'])
    nCir = k.sb("nCir", [128, 8, 32])
    k.cp('dve', nCir[:], nCi[:], ['nCi'], ['nCir'])
    k.pop_scope()
    if hasattr(k, 'rr_cache'):
        del k.rr_cache
    def ring(nm, shape, n, dt=F32):
        return [k.sb(f"{nm}{j}", shape, dt) for j in range(n)]
    FR_ = mybir.dt.float32r
    uTt = ring("uTt", [128, 128], 3)
    uTr = ring("uTr", [128, 128], 3, FR_)
    ut = ring("ut", [128, 128], 5)
    yo = ring("yo", [128, 128], 9)
    m1, m2, m3, m4 = ring("m1_", [128, 512], 3, FR_), ring("m2_", [128, 512], 3, FR_), ring("m3_", [128, 512], 3, FR_), ring("m4_", [128, 512], 3, FR_)
    Xtr, Xti = ring("Xtr", [128, 512], 3), ring("Xti", [128, 512], 3)
    Gr, Gi = ring("Gr", [128, 4, 128], 3), ring("Gi", [128, 4, 128], 3)
    n1, n2, n3, n4 = ring("n1_", [128, 512], 3, FR_), ring("n2_", [128, 512], 3, FR_), ring("n3_", [128, 512], 3, FR_), ring("n4_", [128, 512], 3, FR_)
    Hr, Hi = ring("Hr", [128, 4, 128], 3), ring("Hi", [128, 4, 128], 3)
    cc1 = [k.sb(f"cc1_{h}", [128, 4]) for h in range(2)]
    cc2 = [k.sb(f"cc2_{h}", [128, 4]) for h in range(2)]
    psXr = k.ps("psXr", [128, 512])
    psXi = k.ps("psXi", [128, 512])
    psGr = k.ps("psGr", [128, 512])
    psGi = k.ps("psGi", [128, 512])
    psY = k.ps("psY", [128, 512])
    fl = lambda t: t[:].rearrange("p b t -> p (b t)")

    def item(j):
        i, hc = divmod(j, 2)
        rows = slice(i * 128, (i + 1) * 128)
        cs_ = slice(hc * 512, (hc + 1) * 512)
        bs = slice(hc * 4, (hc + 1) * 4)
        def T(lst, nm):
            q = j % len(lst)
            return lst[q], f'{nm}{q}'
        uT_, kuT = T(uTt, 'uTt'); uR_, kuR = T(uTr, 'uTr'); ut_, kut = T(ut, 'ut'); yo_, kyo = T(yo, 'yo')
        m1_, km1 = T(m1, 'm1'); m2_, km2 = T(m2, 'm2'); m3_, km3 = T(m3, 'm3'); m4_, km4 = T(m4, 'm4')
        Xr_, kXr = T(Xtr, 'Xtr'); Xi_, kXi = T(Xti, 'Xti'); Gr_, kGr = T(Gr, 'Gr'); Gi_, kGi = T(Gi, 'Gi')
        n1_, kn1 = T(n1, 'n1'); n2_, kn2 = T(n2, 'n2'); n3_, kn3 = T(n3, 'n3'); n4_, kn4 = T(n4, 'n4')
        Hr_, kHr = T(Hr, 'Hr'); Hi_, kHi = T(Hi, 'Hi')
        k.dma('sp', uT_[:], uT[hc * 128:(hc + 1) * 128, rows], w=[kuT])
        k.dma('sp', ut_[:], u[rows, hc * 128:(hc + 1) * 128], w=[kut])
        yield
        k.cp('act', uR_[:], uT_[:], [kuT], [kuR])
        yield
        k.mm(psXr[:], uR_[:], BBr[:, hc, :], True, True, [kuR, 'BBr'], ['psXr'])
        k.mm(psXi[:], uR_[:], BBi[:, hc, :], True, True, [kuR, 'BBi'], ['psXi'])
        yield
        k.tt('dve', m1_[:], psXr[:], Pr[:, cs_], ALU.mult, ['psXr', 'Pr'], [km1])
        k.tt('dve', m3_[:], psXr[:], Pi[:, cs_], ALU.mult, ['psXr', 'Pi'], [km3])
        k.tt('dve', m2_[:], psXi[:], Pi[:, cs_], ALU.mult, ['psXi', 'Pi'], [km2])
        k.tt('dve', m4_[:], psXi[:], Pr[:, cs_], ALU.mult, ['psXi', 'Pr'], [km4])
        yield
        k.tt('pool', yo_[:], ut_[:], dbc[:, hc * 128:(hc + 1) * 128], ALU.mult, [kut, 'dbc'], [kyo])
        yield
        for nb in range(4):
            ns = slice(nb * 128, (nb + 1) * 128)
            k.mm(psGr[:, ns], m1_[:, ns], triur[:], True, False, [km1, 'triur'], ['psGr'])
            k.mm(psGr[:, ns], m2_[:, ns], ntriu[:], False, True, [km2, 'ntriu'], ['psGr'])
            k.mm(psGi[:, ns], m3_[:, ns], triur[:], True, False, [km3, 'triur'], ['psGi'])
            k.mm(psGi[:, ns], m4_[:, ns], triur[:], False, True, [km4, 'triur'], ['psGi'])
        yield
        k.tt('dve', Gr_[:], psGr[:].rearrange("p (b t) -> p b t", b=4),
             car_r[:, bs].unsqueeze(2).broadcast_to([128, 4, 128]), ALU.add, ['psGr', f'car_r{hc}'], [kGr])
        k.tt('dve', Gi_[:], psGi[:].rearrange("p (b t) -> p b t", b=4),
             car_i[:, bs].unsqueeze(2).broadcast_to([128, 4, 128]), ALU.add, ['psGi', f'car_i{hc}'], [kGi])
        gr127 = Gr_[:, :, 127]
        gi127 = Gi_[:, :, 127]
        CK = [f'cc1{hc}', f'cc2{hc}']
        k.tt('dve', cc1[hc][:], L128r[:, bs], gr127, ALU.mult, ['L128r', kGr], [CK[0]])
        k.tt('dve', cc2[hc][:], L128i[:, bs], gi127, ALU.mult, ['L128i', kGi], [CK[1]])
        k.tt('dve', car_r[:, bs], cc1[hc][:], cc2[hc][:], ALU.subtract, CK, [f'car_r{hc}'])
        k.tt('dve', cc1[hc][:], L128r[:, bs], gi127, ALU.mult, ['L128r', kGi], [CK[0]])
        k.tt('dve', cc2[hc][:], L128i[:, bs], gr127, ALU.mult, ['L128i', kGr], [CK[1]])
        k.tt('dve', car_i[:, bs], cc1[hc][:], cc2[hc][:], ALU.add, CK, [f'car_i{hc}'])
        yield
        qr = Qr[:, bs, :].rearrange("p b t -> p (b t)")
        qi = Qi[:, bs, :].rearrange("p b t -> p (b t)")
        k.tt('dve', n1_[:], fl(Gr_), qr, ALU.mult, [kGr, 'Qr'], [kn1])
        k.tt('dve', n2_[:], fl(Gi_), qi, ALU.mult, [kGi, 'Qi'], [kn2])
        k.tt('dve', n3_[:], fl(Gi_), qr, ALU.mult, [kGi, 'Qr'], [kn3])
        k.tt('dve', n4_[:], fl(Gr_), qi, ALU.mult, [kGr, 'Qi'], [kn4])
        yield
        for nb in range(4):
            blk = hc * 4 + nb
            ns = slice(nb * 128, (nb + 1) * 128)
            yo_s = psY[:, blk * 32:(blk + 1) * 32]
            k.mm(yo_s, n1_[:, ns], Crr[:, blk, :], True, False, [kn1, 'Crr'], ['psY'])
            k.mm(yo_s, n2_[:, ns], nCr[:, blk, :], False, False, [kn2, 'nCr'], ['psY'])
            k.mm(yo_s, n3_[:, ns], nCir[:, blk, :], False, False, [kn3, 'nCir'], ['psY'])
            k.mm(yo_s, n4_[:, ns], nCir[:, blk, :], False, True, [kn4, 'nCir'], ['psY'])
        yield
        k.tt('dve', yo_[:], yo_[:], psY[:, hc * 128:(hc + 1) * 128], ALU.add, [kyo, 'psY'], [kyo])
        yield
        k.dma('pool', y[rows, hc * 128:(hc + 1) * 128], yo_[:], r=[kyo], final=True)

    yield from pipeline_gen(item, 2 * NT)


def build_S5(L, k=None):
    k = k or K()
    for _ in gen_S5(L, k):
        pass
    return k.finish()


def s5_host_inputs(s, proj_u, prm):
    gs = slice(16 * s, 16 * s + 16)
    cs = slice(256 * s, 256 * s + 256)
    uc = np.ascontiguousarray(proj_u[:, cs])
    Bre = np.zeros((2, 128, 512), np.float32)
    Bim = np.zeros((2, 128, 512), np.float32)
    Cre = np.zeros((8, 128, 32), np.float32)
    Cim = np.zeros((8, 128, 32), np.float32)
    b_re, b_im = prm['s5_b_re'][gs], prm['s5_b_im'][gs]
    c_re, c_im = prm['s5_c_re'][gs], prm['s5_c_im'][gs]
    for g in range(16):
        hc, gl = g // 8, g % 8
        Bre[hc, gl * 16:(gl + 1) * 16, gl * 64:(gl + 1) * 64] = b_re[g].T
        Bim[hc, gl * 16:(gl + 1) * 16, gl * 64:(gl + 1) * 64] = b_im[g].T
        blk, g2 = g // 2, g % 2
        Cre[blk, g2 * 64:(g2 + 1) * 64, g2 * 16:(g2 + 1) * 16] = c_re[g].T
        Cim[blk, g2 * 64:(g2 + 1) * 64, g2 * 16:(g2 + 1) * 16] = c_im[g].T
    return dict(uT=np.ascontiguousarray(uc.T), u=uc,
                lam_re=np.ascontiguousarray(prm['s5_lambda_re'][gs].reshape(-1)),
                lam_im=np.ascontiguousarray(prm['s5_lambda_im'][gs].reshape(-1)),
                lstep=np.ascontiguousarray(np.repeat(prm['s5_log_step'][gs], 64)),
                Bre=Bre, Bim=Bim, Cre=Cre, Cim=Cim, dsk=np.ascontiguousarray(prm['s5_d'][cs]),
                triu=np.triu(np.ones((128, 128), np.float32)),
                iota_p=np.arange(128, dtype=np.float32).reshape(128, 1),
                iota_f=np.tile(np.arange(128, dtype=np.float32)[None], (128, 1)))


GELU_C = 1.5957691216057308


def gen_LRU(L, k):
    TT = 512
    NCH = L // TT
    xbT = k.din("xbT", [256, L])
    gateT = k.din("gateT", [256, L])
    cw_d = k.din("cw", [128, 2, 4])
    cb_d = k.din("cb", [128, 2])
    Wa_d = k.din("Wa", [2, 128, 128])
    Wx_d = k.din("Wx", [2, 128, 128])
    ba_d = k.din("ba", [128, 2])
    bx_d = k.din("bx", [128, 2])
    lam_d = k.din("lam", [128, 2])
    odT = k.dout("odT", [256, L])
    cw = k.sb("cw_s", [128, 2, 4])
    cb = k.sb("cb_s", [128, 2])
    Wa = k.sb("Wa_s", [128, 2, 128])
    Wx = k.sb("Wx_s", [128, 2, 128])
    ba = k.sb("ba_s", [128, 2])
    bx = k.sb("bx_s", [128, 2])
    c8 = k.sb("c8", [128, 2])
    k.dma('sp', cw[:], cw_d, w=['cw'])
    k.dma('sp', cb[:], cb_d, w=['cb'])
    k.dma('sp', Wa[:], Wa_d.rearrange("b p n -> p b n"), w=['Wa'])
    k.dma('sp', Wx[:], Wx_d.rearrange("b p n -> p b n"), w=['Wx'])
    k.dma('sp', ba[:], ba_d, w=['ba'])
    k.dma('sp', bx[:], bx_d, w=['bx'])
    k.dma('sp', c8[:], lam_d, w=['c8'])
    k.act(c8[:], c8[:], AF.Exp, ['c8'], ['c8'], scale=-1.0)
    k.act(c8[:], c8[:], AF.Ln, ['c8'], ['c8'], bias=1.0)
    k.ts('dve', c8[:], c8[:], -8.0, None, ALU.mult, None, ['c8'], ['c8'])
    hlast = k.sb("hlast", [128, 2])
    k.memset('dve', hlast[:], 0.0, ['hlast0', 'hlast1'])

    def ring(nm, shape, n):
        return [k.sb(f"{nm}{j}", shape) for j in range(n)]
    xh = ring("xh", [128, TT + 3], 3)
    gt = ring("gt", [128, TT], 8)
    xc = ring("xc", [128, TT], 5)
    r, ig, a, a2 = ring("r", [128, TT], 2), ring("ig", [128, TT], 3), ring("a", [128, TT], 5), ring("a2", [128, TT], 3)
    bt = ring("bt", [128, TT], 4)
    g2 = ring("g2", [128, TT], 5)
    h = ring("h", [128, TT], 2)
    ot = ring("ot", [128, TT], 3)
    psR = k.ps("psR", [128, TT])
    psI = k.ps("psI", [128, TT])

    def item(n):
        c, pb = divmod(n, 2)
        prow = slice(pb * 128, (pb + 1) * 128)
        def T(lst, nm):
            j = n % len(lst)
            return lst[j], f'{nm}{j}'
        xh_, kxh = T(xh, 'xh'); gt_, kgt = T(gt, 'gt'); xc_, kxc = T(xc, 'xc'); r_, kr = T(r, 'r'); ig_, kig = T(ig, 'ig')
        a_, ka = T(a, 'a'); a2_, ka2 = T(a2, 'a2'); bt_, kbt = T(bt, 'bt'); g2_, kg2 = T(g2, 'g2'); h_, kh = T(h, 'h'); ot_, kot = T(ot, 'ot')
        if c == 0:
            k.memset('dve', xh_[:, 0:3], 0.0, [kxh + 'h'])
            k.dma('sp', xh_[:, 3:TT + 3], xbT[prow, 0:TT], w=[kxh])
        else:
            k.dma('sp', xh_[:, 0:TT + 3], xbT[prow, c * TT - 3:(c + 1) * TT], w=[kxh, kxh + 'h'])
        k.dma('sp', gt_[:], gateT[prow, c * TT:(c + 1) * TT], w=[kgt])
        yield
        xk = [kxh, kxh + 'h']
        k.ts('dve', xc_[:], xh_[:, 3:TT + 3], cw[:, pb, 3:4], cb[:, pb:pb + 1], ALU.mult, ALU.add, xk + ['cw', 'cb'], [kxc])
        for j in (2, 1, 0):
            k.stt(xc_[:], xh_[:, j:j + TT], cw[:, pb, j:j + 1], xc_[:], ALU.mult, ALU.add, xk + ['cw', kxc], [kxc])
        yield
        k.mm(psR[:], Wa[:, pb, :], xc_[:], True, True, ['Wa', kxc], ['psR'])
        k.mm(psI[:], Wx[:, pb, :], xc_[:], True, True, ['Wx', kxc], ['psI'])
        yield
        k.act(r_[:], psR[:], AF.Sigmoid, ['psR', 'ba'], [kr], bias=ba[:, pb:pb + 1])
        k.act(ig_[:], psI[:], AF.Sigmoid, ['psI', 'bx'], [kig], bias=bx[:, pb:pb + 1])
        k.act(a_[:], r_[:], AF.Exp, [kr, 'c8'], [ka], scale=c8[:, pb:pb + 1])
        k.act(a2_[:], a_[:], AF.Square, [ka], [ka2])
        k.act(a2_[:], a2_[:], AF.Sqrt, [ka2], [ka2], scale=-1.0, bias=1.0)
        k.act(g2_[:], gt_[:], AF.Square, [kgt], [kg2])
        k.act(g2_[:], g2_[:], AF.Copy, [kg2], [kg2], scale=0.044715, bias=1.0)
        yield
        k.tt('dve', bt_[:], ig_[:], xc_[:], ALU.mult, [kig, kxc], [kbt])
        k.tt('dve', bt_[:], bt_[:], a2_[:], ALU.mult, [kbt, ka2], [kbt])
        k.tt('dve', g2_[:], g2_[:], gt_[:], ALU.mult, [kg2, kgt], [kg2])
        yield
        k.act(g2_[:], g2_[:], AF.Sigmoid, [kg2], [kg2], scale=GELU_C)
        yield
        k.P.op('dve', lambda e: e.tensor_tensor_scan(out=h_[:], data0=a_[:], data1=bt_[:], initial=hlast[:, pb:pb + 1],
                                                     op0=ALU.mult, op1=ALU.add),
               reads=[ka, kbt, f'hlast{pb}'], writes=[kh])
        k.cp('dve', hlast[:, pb:pb + 1], h_[:, TT - 1:TT], [kh], [f'hlast{pb}'])
        k.tt('dve', g2_[:], g2_[:], gt_[:], ALU.mult, [kg2, kgt], [kg2])
        k.tt('dve', ot_[:], h_[:], g2_[:], ALU.mult, [kh, kg2], [kot])
        yield
        k.dma('pool', odT[prow, c * TT:(c + 1) * TT], ot_[:], r=[kot], final=True)

    yield from pipeline_gen(item, 2 * NCH)


def build_LRU(L, k=None):
    k = k or K()
    for _ in gen_LRU(L, k):
        pass
    return k.finish()


def lru_host_inputs(s, xb, gate, prm):
    cs = slice(256 * s, 256 * s + 256)
    col = lambda v: np.ascontiguousarray(v[cs].reshape(2, 128).T)
    Wa = np.zeros((2, 128, 128), np.float32)
    Wx = np.zeros((2, 128, 128), np.float32)
    for pb in range(2):
        for bl in range(2):
            blk = 4 * s + 2 * pb + bl
            Wa[pb, bl * 64:(bl + 1) * 64, bl * 64:(bl + 1) * 64] = prm['lru_w_a'][blk]
            Wx[pb, bl * 64:(bl + 1) * 64, bl * 64:(bl + 1) * 64] = prm['lru_w_x'][blk]
    cw = np.ascontiguousarray(prm['lru_conv_w'][:, cs].reshape(4, 2, 128).transpose(2, 1, 0))
    return dict(xbT=np.ascontiguousarray(xb[:, cs].T), gateT=np.ascontiguousarray(gate[:, cs].T), cw=cw,
                cb=col(prm['lru_conv_b']), Wa=Wa, Wx=Wx, ba=col(prm['lru_b_a']), bx=col(prm['lru_b_x']),
                lam=col(prm['lru_lambda']))


GN_EPS = 64e-5
NLEV = 5


def build_RWKV(L, k=None, NH=4, fr=False, CH=64):
    k = k or K()
    NT = L // 128
    W = NH * 64
    NG = NH // 4
    FR = mybir.dt.float32r if fr else F32
    rd = (lambda ap: ap.bitcast(F32)) if fr else (lambda ap: ap)
    NCK = 128 // CH
    nlev = 5 if CH == 64 else 6
    frc = fr and CH == 128
    FRC = mybir.dt.float32r if frc else F32
    rdc = (lambda ap: ap.bitcast(F32)) if frc else (lambda ap: ap)
    lhc = (lambda ap: ap) if frc else rd
    prkv = [k.din(nm, [L, W]) for nm in ("pr", "pk", "pv")]
    mu1 = k.din("mu1", [3 * W])
    pls = [k.din("plw", [64, L]), k.din("pla", [64, L]), k.din("plg", [128, L])]
    mul = k.din("mul", [128, 3])
    w2 = k.din("w2", [64, W])
    a2 = k.din("a2", [64, W])
    g2 = k.din("g2", [128, W])
    vecs = k.din("vecs", [7, W])
    ident_d = k.din("ident", [128, 128])
    triw_d = k.din("triw", [3, 128, 128])
    mask5_d = k.din("mask5", [128, 640])
    rowm_d = k.din("rowm", [128, 2])
    oc = k.dout("oc", [L, W])

    k.consts(ident_d)
    triw = k.sb("triw_s", [128, 3, 128])
    k.dma('sp', triw[:], triw_d.rearrange("a p n -> p a n"), w=['triw'])
    mask5 = k.sb("mask5_s", [128, 640])
    k.dma('sp', mask5[:], mask5_d, w=['mask5'])
    rowm = k.sb("rowm_s", [128, 2])
    k.dma('sp', rowm[:], rowm_d, w=['rowm'])
    mu1bc = k.bcast_row("mu1bc", mu1, 3 * W)
    vb = [k.bcast_row(f"vb{i}", vecs[i], W) for i in range(7)]
    w0bc, a0bc, kkbc, kabc, rkbc, lngbc, lnbbc = vb
    VK = [f"vb{i}" for i in range(7)]
    muls = k.sb("muls", [128, 3])
    k.dma('sp', muls[:], mul, w=['muls'])
    w2s = k.sb("w2s", [64, W])
    a2s = k.sb("a2s", [64, W])
    k.dma('sp', w2s[:], w2, w=['w2s'])
    k.dma('sp', a2s[:], a2, w=['a2s'])
    g2s = k.sb("g2s", [128, W])
    k.dma('sp', g2s[:], g2, w=['g2s'])
    ST = [k.sb(f"ST{i}", [64, 64], FRC) for i in range(NH)]
    zt = k.sb("zt", [128, W])
    k.memset('dve', zt[:], 0.0, ['zt'])
    for i in range(NH):
        k.cp('dve', ST[i][:], zt[0:64, 0:64], ['zt'], [f'ST{i}'])
    P1s = k.sb("P1s", [128, W], FRC)
    Us = k.sb("Us", [128, W], FRC)
    k.cp('dve', P1s[:], zt[:], ['zt'], ['P1s'])
    k.cp('dve', Us[:], zt[:], ['zt'], ['Us'])

    pt = [k.sb(f"pt{i}", [128, 3 * W]) for i in range(2)]
    pp = [k.sb(f"pp{i}", [128, 3 * W]) for i in range(2)]
    lt = [k.sb(f"lt{i}", [128, 3, 128]) for i in range(2)]
    lp = [k.sb(f"lp{i}", [128, 3, 128]) for i in range(2)]
    for i_ in range(2):
        k.memset('pool', lt[i_][:], 0.0, [f'lt{i_}0', f'lt{i_}1', f'lt{i_}2'])
        k.memset('pool', lp[i_][:], 0.0, [f'lp{i_}0', f'lp{i_}1', f'lp{i_}2', f'lp{i_}z'])
    pm = k.sb("pm", [128, 3 * W])
    vr = k.sb("vr", [128, W], FR)
    lm = k.sb("lm", [128, 3, 128])
    sw = k.sb("sw", [128, W])
    av = k.sb("av", [128, W])
    gv = k.sb("gv", [128, W])
    kkr = k.sb("kkr", [128, W])
    sq = k.sb("sq", [128, W])
    s4 = k.sb("s4", [128, NH])
    rn = k.sb("rn", [128, NH])
    nkk = k.sb("nkk", [128, W])
    kmod = k.sb("kmod", [128, W])
    kka = k.sb("kka", [128, W])
    tmp = k.sb("tmp", [128, W])
    bon = k.sb("bon", [128, NH])
    E1 = k.sb("E1", [128, W])
    E2 = k.sb("E2", [128, W])
    E3 = k.sb("E3", [128, W])
    E4 = k.sb("E4", [128, W])
    E1T = k.sb("E1T", [64, NH, 128])
    At = k.sb("At", [128, W])
    Bs = k.sb("Bs", [128, W])
    Ks = k.sb("Ks", [128, W])
    Rt = k.sb("Rt", [128, W])
    Bfm = [k.sb(f"Bfm{c}", [128, W]) for c in range(2)]
    Kfm = [k.sb(f"Kfm{c}", [128, W]) for c in range(2)]
    FT = [k.sb(f"FT{h}", [64, 4, 128], FR) for h in range(NH)]
    A5 = [k.sb(f"A5_{h}", [128, 640], FR) for h in range(NH)]
    NL = [k.sb(f"NL_{h}", [128, 256], FR) for h in range(NH)]
    PQ = [k.sb(f"PQ_{h}", [128, 256], FR) for h in range(NH)]
    W1 = k.sb("W1", [128, W], FR)
    U1 = k.sb("U1", [128, W])
    ysb = k.sb("ysb", [128, W])
    yc = k.sb("yc", [128, W])
    m4 = k.sb("m4", [128, NH])
    r4 = k.sb("r4", [128, NH])
    ot = [k.sb(f"ot{i}", [128, W]) for i in range(2)]
    B = [k.ps(f"psB{i}", [128, 512]) for i in range(8)]
    bk = lambda i: f'psB{i}'
    v3 = lambda t: t.rearrange("p (h j) -> p h j", h=NH)
    bc4 = lambda t: t.unsqueeze(2).broadcast_to([128, NH, 64])

    for i in range(NT):
        b = i % 2
        rows = slice(i * 128, (i + 1) * 128)
        PK, PPK, LTK, LPK = [], [], [], []
        for q in range(3):
            cq = slice(q * W, (q + 1) * W)
            k.dma('sp', pt[b][:, cq], prkv[q][rows, :], w=[f'pt{b}{q}'])
            PK.append(f'pt{b}{q}')
            if i == 0:
                k.dma('sp', pp[b][1:128, cq], prkv[q][0:127, :], w=[f'pp{b}{q}'])
            else:
                k.dma('sp', pp[b][:, cq], prkv[q][i * 128 - 1:i * 128 + 127, :], w=[f'pp{b}{q}'])
            PPK.append(f'pp{b}{q}')
            nr = pls[q].shape[0]
            k.dma('sp', lt[b][0:nr, q, :], pls[q][:, rows], w=[f'lt{b}{q}'])
            LTK.append(f'lt{b}{q}')
            if i == 0:
                k.dma('sp', lp[b][0:nr, q, 1:128], pls[q][:, 0:127], w=[f'lp{b}{q}'])
            else:
                k.dma('sp', lp[b][0:nr, q, :], pls[q][:, i * 128 - 1:i * 128 + 127], w=[f'lp{b}{q}'])
            LPK.append(f'lp{b}{q}')
        if i == 0:
            k.memset('pool', pp[b][0:1, :], 0.0, [f'pp{b}z'])
            k.memset('pool', lp[b][:, :, 0:1], 0.0, [f'lp{b}z'])
            PPK.append(f'pp{b}z')
            LPK.append(f'lp{b}z')
        k.tt('pool', pm[:], pp[b][:], pt[b][:], ALU.subtract, PPK + PK, ['pm'])
        k.tt('pool', pm[:], pm[:], mu1bc[:], ALU.mult, ['pm', 'mu1bc'], ['pm'])
        k.tt('pool', pm[:], pm[:], pt[b][:], ALU.add, ['pm'] + PK, ['pm'])
        r_, k_, v_ = pm[:, 0:W], pm[:, W:2 * W], pm[:, 2 * W:3 * W]
        k.cp('act', vr[:], v_, ['pm'], ['vr'])
        LK = LTK + LPK
        k.tt('dve', lm[:], lp[b][:], lt[b][:], ALU.subtract, LK, ['lm'])
        for blk in range(3):
            k.stt(lm[:, blk, :], lm[:, blk, :], muls[:, blk:blk + 1], lt[b][:, blk, :], ALU.mult, ALU.add,
                  ['lm', 'muls'] + LK, ['lm'])
        k.act(lm[0:64, 0, :], lm[0:64, 0, :], AF.Tanh, ['lm'], ['lm'])
        k.act(lm[:, 2, :], lm[:, 2, :], AF.Sigmoid, ['lm'], ['lm'])
        k.mm(B[0][:, 0:W], lm[0:64, 0, :], w2s[:], True, True, ['lm', 'w2s'], [bk(0)])
        k.mm(B[1][:, 0:W], lm[0:64, 1, :], a2s[:], True, True, ['lm', 'a2s'], [bk(1)])
        k.mm(B[2][:, 0:W], lm[:, 2, :], g2s[:], True, True, ['lm', 'g2s'], [bk(2)])
        k.tt('dve', sw[:], B[0][:, 0:W], w0bc[:], ALU.add, [bk(0), VK[0]], ['sw'])
        k.act(sw[:], sw[:], AF.Sigmoid, ['sw'], ['sw'])
        k.tt('dve', av[:], B[1][:, 0:W], a0bc[:], ALU.add, [bk(1), VK[1]], ['av'])
        k.act(av[:], av[:], AF.Sigmoid, ['av'], ['av'])
        k.cp('act', gv[:], B[2][:, 0:W], [bk(2)], ['gv'])
        k.tt('pool', kkr[:], k_, kkbc[:], ALU.mult, ['pm', VK[2]], ['kkr'])
        k.tt('pool', sq[:], kkr[:], kkr[:], ALU.mult, ['kkr'], ['sq'])
        k.P.op('dve', lambda e: e.tensor_reduce(out=s4[:], in_=v3(sq[:]), axis=AX.X, op=ALU.add), reads=['sq'], writes=['s4'])
        k.act(s4[:], s4[:], AF.Sqrt, ['s4'], ['s4'])
        k.ts('dve', s4[:], s4[:], 1e-12, None, ALU.max, None, ['s4'], ['s4'])
        k.recip(rn[:], s4[:], ['s4'], ['rn'])
        k.ts('dve', rn[:], rn[:], -1.0, None, ALU.mult, None, ['rn'], ['rn'])
        k.tt('dve', v3(nkk[:]), v3(kkr[:]), bc4(rn[:]), ALU.mult, ['kkr', 'rn'], ['nkk'])
        k.stt(tmp[:], av[:], -1.0, kabc[:], ALU.add, ALU.mult, ['av', VK[3]], ['tmp'])
        k.stt(kmod[:], tmp[:], 1.0, k_, ALU.add, ALU.mult, ['tmp', 'pm'], ['kmod'])
        k.stt(kka[:], nkk[:], -1.0, av[:], ALU.mult, ALU.mult, ['nkk', 'av'], ['kka'])
        k.tt('pool', tmp[:], r_, kmod[:], ALU.mult, ['pm', 'kmod', 'tmp'], ['tmp'])
        k.tt('pool', tmp[:], tmp[:], rkbc[:], ALU.mult, ['tmp', VK[4]], ['tmp'])
        k.P.op('dve', lambda e: e.tensor_reduce(out=bon[:], in_=v3(tmp[:]), axis=AX.X, op=ALU.add), reads=['tmp'], writes=['bon'])
        k.mm(B[3][:, 0:W], triw[:, 0, :], sw[:], True, True, ['triw', 'sw'], [bk(3)])
        k.mm(B[4][:, 0:W], triw[:, 1, :], sw[:], True, True, ['triw', 'sw'], [bk(4)])
        k.mm(B[5][:, 0:W], triw[:, 2, :], sw[:], True, True, ['triw', 'sw'], [bk(5)])
        for h in range(NH):
            k.mm(B[6 + h // 4][0:64, (h % 4) * 128:(h % 4 + 1) * 128], sw[:, h * 64:(h + 1) * 64], triw[:, 0, :], True, True,
                 ['sw', 'triw'], [bk(6 + h // 4)])
        k.act(E1[:], B[3][:, 0:W], AF.Exp, [bk(3)], ['E1'])
        k.act(E2[:], B[3][:, 0:W], AF.Exp, [bk(3)], ['E2'], scale=-1.0)
        k.act(E3[:], B[4][:, 0:W], AF.Exp, [bk(4)], ['E3'])
        k.act(E4[:], B[5][:, 0:W], AF.Exp, [bk(5)], ['E4'])
        for g in range(NG):
            k.act(E1T[:, 4 * g:4 * g + 4, :].rearrange("p a t -> p (a t)"), B[6 + g][0:64, :], AF.Exp, [bk(6 + g)], ['E1T'])
        k.tt('dve', At[:], nkk[:], E3[:], ALU.mult, ['nkk', 'E3'], ['At'])
        k.tt('pool', Bs[:], kka[:], E2[:], ALU.mult, ['kka', 'E2'], ['Bs'])
        k.tt('dve', Ks[:], kmod[:], E2[:], ALU.mult, ['kmod', 'E2'], ['Ks'])
        k.tt('pool', Rt[:], r_, E1[:], ALU.mult, ['pm', 'E1'], ['Rt'])
        for c in range(NCK):
            k.stt(Bfm[c][:], kka[:], rowm[:, c:c + 1], E4[:], ALU.mult, ALU.mult, ['kka', 'E4', 'rowm'], [f'Bfm{c}'])
            k.stt(Kfm[c][:], kmod[:], rowm[:, c:c + 1], E4[:], ALU.mult, ALU.mult, ['kmod', 'E4', 'rowm'], [f'Kfm{c}'])
        HS = list(range(NH))
        for h in HS:
            cs_ = slice(h * 64, (h + 1) * 64)
            for q, (src, key) in enumerate([(At, 'At'), (Bs, 'Bs'), (Ks, 'Ks'), (Rt, 'Rt')]):
                k.tr(B[h][0:64, q * 128:(q + 1) * 128], src[:, cs_], k.identf[:], [key], [bk(h)])
        for h in HS:
            k.cp('act' if h % 2 else 'dve', FT[h][:].rearrange("p a t -> p (a t)"), B[h][0:64, :], [bk(h)], [f'FT{h}'])
        for h in HS:
            AtT, BsT, KsT, RtT = (FT[h][:, q, :] for q in range(4))
            o = lambda j: B[h][:, j * 128:(j + 1) * 128]
            k.mm(o(0), BsT, AtT, True, True, [f'FT{h}'], [bk(h)])
            k.mm(o(1), AtT, BsT, True, True, [f'FT{h}'], [bk(h)])
            k.mm(o(2), KsT, AtT, True, True, [f'FT{h}'], [bk(h)])
        for h in HS:
            k.tt('dve', A5[h][:, 0:384], B[h][:, 0:384], mask5[:, 0:384], ALU.mult, [bk(h), 'mask5'], [f'A5_{h}'])
        for h in HS:
            AtT, BsT, KsT, RtT = (FT[h][:, q, :] for q in range(4))
            k.mm(B[h][:, 0:128], BsT, RtT, True, True, [f'FT{h}'], [bk(h)])
            k.mm(B[h][:, 128:256], KsT, RtT, True, True, [f'FT{h}'], [bk(h)])
        for h in HS:
            k.tt('dve', A5[h][:, 384:640], B[h][:, 0:256], mask5[:, 384:640], ALU.mult, [bk(h), 'mask5'], [f'A5b_{h}'])
            k.cp('act', NL[h][:], rd(A5[h][:, 0:256]), [f'A5_{h}'], [f'NL_{h}'])
            k.tt('pool' if not fr else 'dve', PQ[h][:].rearrange("p (a n) -> p a n", a=2), rd(A5[h][:, 0:256]).rearrange("p (a n) -> p a n", a=2),
                 k.identf[:].unsqueeze(1).broadcast_to([128, 2, 128]), ALU.add, [f'A5_{h}', 'ident'], [f'PQ_{h}'])
        for lev in range(nlev):
            for h in HS:
                N_, L_ = NL[h][:, 0:128], NL[h][:, 128:256]
                k.mm(B[h][:, 0:128], L_, N_, True, True, [f'NL_{h}'], [bk(h)])
                k.mm(B[h][:, 128:256], N_, L_, True, True, [f'NL_{h}'], [bk(h)])
            for h in HS:
                k.cp('act', NL[h][:], B[h][:, 0:256], [bk(h)], [f'NL_{h}'])
            for h in HS:
                N_, L_ = NL[h][:, 0:128], NL[h][:, 128:256]
                P_, Q_ = PQ[h][:, 0:128], PQ[h][:, 128:256]
                k.mm(B[h][:, 256:384], Q_, N_, True, True, [f'NL_{h}', f'PQ_{h}'], [bk(h)])
                k.mm(B[h][:, 384:512], P_, L_, True, True, [f'NL_{h}', f'PQ_{h}'], [bk(h)])
            for h in HS:
                k.tt('dve', PQ[h][:], B[h][:, 256:512], rd(PQ[h][:]), ALU.add, [bk(h), f'PQ_{h}'], [f'PQ_{h}'])
        for h in range(NH):
            k.mm(B[0][:, h * 64:(h + 1) * 64], A5[h][:, 256:384], vr[:, h * 64:(h + 1) * 64], True, True, [f'A5_{h}', 'vr'], [bk(0)])
        k.cp('act', W1[:], B[0][:, 0:W], [bk(0)], ['W1'])
        for h in range(NH):
            k.mm(B[1][:, h * 64:(h + 1) * 64], PQ[h][:, 0:128], W1[:, h * 64:(h + 1) * 64], True, True,
                 [f'PQ_{h}', 'W1'], [bk(1)])
        k.cp('act', U1[:], B[1][:, 0:W], [bk(1)], ['U1'])
        vsrc = vr if frc else None
        for c in range(NCK):
            cr = slice(c * CH, (c + 1) * CH)
            for h in range(NH):
                k.mm(B[2][cr, h * 64:(h + 1) * 64], lhc(FT[h][:, 0, cr]), ST[h][:], True, True, [f'FT{h}', f'ST{h}'], [bk(2)])
            k.cp('act', P1s[cr, :], B[2][cr, 0:W], [bk(2)], ['P1s'])
            for h in range(NH):
                k.mm(B[3][cr, h * 64:(h + 1) * 64], lhc(PQ[h][:, cr]), P1s[:, h * 64:(h + 1) * 64], True, True,
                     [f'PQ_{h}', 'P1s'], [bk(3)])
            k.tt('dve', Us[cr, :], B[3][cr, 0:W], U1[cr, :], ALU.add, [bk(3), 'U1'], ['Us'])
            for h in range(NH):
                hc_ = slice(h * 64, (h + 1) * 64)
                vh = vr[:, hc_] if frc else pm[:, 2 * W + h * 64:2 * W + (h + 1) * 64]
                vk = 'vr' if frc else 'pm'
                k.mm(B[6][cr, hc_], lhc(FT[h][:, 3, cr]), ST[h][:], True, False, [f'FT{h}', f'ST{h}'], [bk(6)])
                k.mm(B[6][cr, hc_], lhc(A5[h][:, 384:512][:, cr]), Us[:, hc_], False, False, [f'A5b_{h}', 'Us'], [bk(6)])
                k.mm(B[6][cr, hc_], lhc(A5[h][:, 512:640][:, cr]), vh, False, True, [f'A5b_{h}', vk], [bk(6)])
            for h in range(NH):
                hc_ = slice(h * 64, (h + 1) * 64)
                vh = pm[:, 2 * W + h * 64:2 * W + (h + 1) * 64]
                k.mm(B[7][0:64, hc_], Bfm[c][:, hc_], rdc(Us[:, hc_]), True, False, [f'Bfm{c}', 'Us'], [bk(7)])
                k.mm(B[7][0:64, hc_], Kfm[c][:, hc_], vh, False, True, [f'Kfm{c}', 'pm'], [bk(7)])
            for h in range(NH):
                hc_ = slice(h * 64, (h + 1) * 64)
                k.stt(ST[h][:], rdc(ST[h][:]), E1T[:, h, (c + 1) * CH - 1:(c + 1) * CH], B[7][0:64, hc_], ALU.mult, ALU.add,
                      [f'ST{h}', 'E1T', bk(7)], [f'ST{h}'])
        k.cp('act', ysb[:], B[6][:, 0:W], [bk(6)], ['ysb'])
        k.P.op('dve', lambda e: e.tensor_reduce(out=m4[:], in_=v3(ysb[:]), axis=AX.X, op=ALU.add), reads=['ysb'], writes=['m4'])
        k.ts('dve', m4[:], m4[:], -1.0 / 64.0, None, ALU.mult, None, ['m4'], ['m4'])
        k.tt('dve', v3(yc[:]), v3(ysb[:]), bc4(m4[:]), ALU.add, ['ysb', 'm4'], ['yc'])
        k.tt('pool', sq[:], yc[:], yc[:], ALU.mult, ['yc'], ['sq'])
        k.P.op('dve', lambda e: e.tensor_reduce(out=r4[:], in_=v3(sq[:]), axis=AX.X, op=ALU.add), reads=['sq'], writes=['r4'])
        k.ts('dve', r4[:], r4[:], 1.0 / 64.0, GN_EPS, ALU.mult, ALU.add, ['r4'], ['r4'])
        k.act(r4[:], r4[:], AF.Sqrt, ['r4'], ['r4'])
        k.recip(r4[:], r4[:], ['r4'], ['r4'])
        k.tt('dve', v3(yc[:]), v3(yc[:]), bc4(r4[:]), ALU.mult, ['yc', 'r4'], ['yc'])
        k.tt('pool', yc[:], yc[:], lngbc[:], ALU.mult, ['yc', VK[5]], ['yc'])
        k.tt('pool', yc[:], yc[:], lnbbc[:], ALU.add, ['yc', VK[6]], ['yc'])
        k.tt('dve', v3(tmp[:]), v3(v_), bc4(bon[:]), ALU.mult, ['pm', 'bon', 'tmp'], ['tmp'])
        k.tt('pool', yc[:], yc[:], tmp[:], ALU.add, ['yc', 'tmp'], ['yc'])
        k.tt('dve', ot[b][:], yc[:], gv[:], ALU.mult, ['yc', 'gv'], [f'ot{b}'])
        k.dma('pool', oc[rows, :], ot[b][:], r=[f'ot{b}'], final=True)
    return k.finish()


def build_RWKVP(L, k=None, CH=64):
    NH, fr = 8, True
    k = k or K()
    NT = L // 128
    W = NH * 64
    NG = NH // 4
    FR = mybir.dt.float32r if fr else F32
    rd = (lambda ap: ap.bitcast(F32)) if fr else (lambda ap: ap)
    NCK = 128 // CH
    nlev = 5 if CH == 64 else 6
    frc = True
    FRC = mybir.dt.float32r if frc else F32
    rdc = (lambda ap: ap.bitcast(F32)) if frc else (lambda ap: ap)
    lhc = (lambda ap: ap) if frc else rd
    prkv = [k.din(nm, [L, W]) for nm in ("pr", "pk", "pv")]
    mu1 = k.din("mu1", [3 * W])
    pls = [k.din("plw", [64, L]), k.din("pla", [64, L]), k.din("plg", [128, L])]
    mul = k.din("mul", [128, 3])
    w2 = k.din("w2", [64, W])
    a2 = k.din("a2", [64, W])
    g2 = k.din("g2", [128, W])
    vecs = k.din("vecs", [7, W])
    ident_d = k.din("ident", [128, 128])
    triw_d = k.din("triw", [3, 128, 128])
    mask5_d = k.din("mask5", [128, 640])
    rowm_d = k.din("rowm", [128, 2])
    oc = k.dout("oc", [L, W])

    k.consts(ident_d)
    triw = k.sb("triw_s", [128, 3, 128])
    k.dma('sp', triw[:], triw_d.rearrange("a p n -> p a n"), w=['triw'])
    mask5 = k.sb("mask5_s", [128, 640])
    k.dma('sp', mask5[:], mask5_d, w=['mask5'])
    rowm = k.sb("rowm_s", [128, 2])
    k.dma('sp', rowm[:], rowm_d, w=['rowm'])
    mu1bc = k.bcast_row("mu1bc", mu1, 3 * W)
    vb = [k.bcast_row(f"vb{i}", vecs[i], W) for i in range(7)]
    w0bc, a0bc, kkbc, kabc, rkbc, lngbc, lnbbc = vb
    VK = [f"vb{i}" for i in range(7)]
    muls = k.sb("muls", [128, 3])
    k.dma('sp', muls[:], mul, w=['muls'])
    w2s = k.sb("w2s", [64, W])
    a2s = k.sb("a2s", [64, W])
    k.dma('sp', w2s[:], w2, w=['w2s'])
    k.dma('sp', a2s[:], a2, w=['a2s'])
    g2s = k.sb("g2s", [128, W])
    k.dma('sp', g2s[:], g2, w=['g2s'])
    ST = [k.sb(f"ST{i}", [64, 64], FRC) for i in range(NH)]
    zt = k.sb("zt", [128, W])
    k.memset('dve', zt[:], 0.0, ['zt'])
    for i in range(NH):
        k.cp('dve', ST[i][:], zt[0:64, 0:64], ['zt'], [f'ST{i}'])
    P1s = k.sb("P1s", [128, W], FRC)
    Us = k.sb("Us", [128, W], FRC)
    k.cp('dve', P1s[:], zt[:], ['zt'], ['P1s'])
    k.cp('dve', Us[:], zt[:], ['zt'], ['Us'])

    pt = [k.sb("pt0", [128, 3 * W])] * 2
    pp = [k.sb("pp0", [128, 3 * W])] * 2
    lt = [k.sb("lt0", [128, 3, 128])] * 2
    lp = [k.sb("lp0", [128, 3, 128])] * 2
    k.memset('pool', lt[0][:], 0.0, ['lt0', 'lt1', 'lt2'])
    k.memset('pool', lp[0][:], 0.0, ['lp0', 'lp1', 'lp2', 'lpz'])
    pm2 = [k.sb(f"pm{i_}", [128, 3 * W]) for i_ in range(2)]
    vr2 = [k.sb(f"vr{i_}", [128, W], FR) for i_ in range(2)]
    lm2 = [k.sb(f"lm{i_}", [128, 3, 128]) for i_ in range(2)]
    sw = k.sb("sw", [128, W])
    av = k.sb("av", [128, W])
    gv2 = [k.sb(f"gv{i_}", [128, W]) for i_ in range(2)]
    kkr = k.sb("kkr", [128, W])
    sq = k.sb("sq", [128, W])
    s4 = k.sb("s4", [128, NH])
    rn = k.sb("rn", [128, NH])
    nkk = k.sb("nkk", [128, W])
    kmod = k.sb("kmod", [128, W])
    kka = k.sb("kka", [128, W])
    tmp = k.sb("tmp", [128, W])
    bon2 = [k.sb(f"bon{i_}", [128, NH]) for i_ in range(2)]
    E1 = k.sb("E1", [128, W])
    E2 = k.sb("E2", [128, W])
    E3 = k.sb("E3", [128, W])
    E4 = k.sb("E4", [128, W])
    E1T2 = [k.sb(f"E1T{i_}", [64, NH, 128]) for i_ in range(2)]
    At2 = [k.sb(f"At{i_}", [128, W]) for i_ in range(2)]
    Bs2 = [k.sb(f"Bs{i_}", [128, W]) for i_ in range(2)]
    Ks2 = [k.sb(f"Ks{i_}", [128, W]) for i_ in range(2)]
    Rt2 = [k.sb(f"Rt{i_}", [128, W]) for i_ in range(2)]
    Bfm2 = [[k.sb(f"Bfm{p_}{c}", [128, W]) for c in range(NCK)] for p_ in range(2)]
    Kfm2 = [[k.sb(f"Kfm{p_}{c}", [128, W]) for c in range(NCK)] for p_ in range(2)]
    sqp = k.sb("sqp", [128, W])
    tmpp = k.sb("tmpp", [128, W])
    FT = [k.sb(f"FT{h}", [64, 4, 128], FR) for h in range(NH)]
    A5 = [k.sb(f"A5_{h}", [128, 640], FR) for h in range(NH)]
    NL = [k.sb(f"NL_{h}", [128, 256], FR) for h in range(NH)]
    PQ = [k.sb(f"PQ_{h}", [128, 128], FR) for h in range(NH)]
    W1 = k.sb("W1", [128, W], FR)
    U1 = k.sb("U1", [128, W])
    ysb = k.sb("ysb", [128, W])
    yc = k.sb("yc", [128, W])
    m4 = k.sb("m4", [128, NH])
    r4 = k.sb("r4", [128, NH])
    ot = [k.sb(f"ot{i}", [128, W]) for i in range(2)]
    B = [k.ps(f"psB{i}", [128, 512]) for i in range(8)]
    bk = lambda i: f'psB{i}'
    v3 = lambda t: t.rearrange("p (h j) -> p h j", h=NH)
    bc4 = lambda t: t.unsqueeze(2).broadcast_to([128, NH, 64])


    S0, S1, C0, C1 = 6, 7, 4, 5

    def tile(i):
        b = i % 2
        pm, lm = pm2[b], lm2[b]
        kpm, klm = f'pm{b}', f'lm{b}'
        At, Bs, Ks, Rt, gv, vr, bon, E1T, Bf, Kf = At2[b], Bs2[b], Ks2[b], Rt2[b], gv2[b], vr2[b], bon2[b], E1T2[b], Bfm2[b], Kfm2[b]
        kAt, kBs, kKs, kRt, kgv, kvr, kbon, kE1T, kBf, kKf = (f'{n_}{b}' for n_ in ('At', 'Bs', 'Ks', 'Rt', 'gv', 'vr', 'bon', 'E1T', 'Bf', 'Kf'))
        rows = slice(i * 128, (i + 1) * 128)
        PK, PPK, LTK, LPK = [], [], [], []
        for q in range(3):
            cq = slice(q * W, (q + 1) * W)
            k.dma('sp', pt[b][:, cq], prkv[q][rows, :], w=[f'pt{q}'])
            PK.append(f'pt{q}')
            if i == 0:
                k.dma('sp', pp[b][1:128, cq], prkv[q][0:127, :], w=[f'pp{q}'])
            else:
                k.dma('sp', pp[b][:, cq], prkv[q][i * 128 - 1:i * 128 + 127, :], w=[f'pp{q}'])
            PPK.append(f'pp{q}')
            nr = pls[q].shape[0]
            k.dma('sp', lt[b][0:nr, q, :], pls[q][:, rows], w=[f'lt{q}'])
            LTK.append(f'lt{q}')
            if i == 0:
                k.dma('sp', lp[b][0:nr, q, 1:128], pls[q][:, 0:127], w=[f'lp{q}'])
            else:
                k.dma('sp', lp[b][0:nr, q, :], pls[q][:, i * 128 - 1:i * 128 + 127], w=[f'lp{q}'])
            LPK.append(f'lp{q}')
        if i == 0:
            k.memset('pool', pp[b][0:1, :], 0.0, ['ppz'])
            k.memset('pool', lp[b][:, :, 0:1], 0.0, ['lpz'])
            PPK.append('ppz')
            LPK.append('lpz')
        k.tt('dve', pm[:], pp[b][:], pt[b][:], ALU.subtract, PPK + PK, [kpm])
        k.tt('dve', pm[:], pm[:], mu1bc[:], ALU.mult, [kpm, 'mu1bc'], [kpm])
        k.tt('dve', pm[:], pm[:], pt[b][:], ALU.add, [kpm] + PK, [kpm])
        r_, k_, v_ = pm[:, 0:W], pm[:, W:2 * W], pm[:, 2 * W:3 * W]
        LK = LTK + LPK
        k.tt('dve', lm[:], lp[b][:], lt[b][:], ALU.subtract, LK, [klm])
        for blk in range(3):
            k.stt(lm[:, blk, :], lm[:, blk, :], muls[:, blk:blk + 1], lt[b][:, blk, :], ALU.mult, ALU.add,
                  [klm, 'muls'] + LK, [klm])
        k.act(lm[0:64, 0, :], lm[0:64, 0, :], AF.Tanh, [klm], [klm])
        k.act(lm[:, 2, :], lm[:, 2, :], AF.Sigmoid, [klm], [klm])
        yield 'STAGE'
        k.cp('act', vr[:], v_, [kpm], [kvr])
        k.mm(B[S0][:, 0:W], lm[0:64, 0, :], w2s[:], True, True, [klm, 'w2s'], [bk(S0)])
        k.mm(B[S1][:, 0:W], lm[0:64, 1, :], a2s[:], True, True, [klm, 'a2s'], [bk(S1)])
        yield 'sub'
        k.tt('dve', sw[:], B[S0][:, 0:W], w0bc[:], ALU.add, [bk(S0), VK[0]], ['sw'])
        k.act(sw[:], sw[:], AF.Sigmoid, ['sw'], ['sw'])
        k.tt('dve', av[:], B[S1][:, 0:W], a0bc[:], ALU.add, [bk(S1), VK[1]], ['av'])
        k.act(av[:], av[:], AF.Sigmoid, ['av'], ['av'])
        yield 'sub'
        k.mm(B[S0][:, 0:W], lm[:, 2, :], g2s[:], True, True, [klm, 'g2s'], [bk(S0)])
        k.cp('act', gv[:], B[S0][:, 0:W], [bk(S0)], [kgv])
        yield 'sub'
        k.tt('dve', kkr[:], k_, kkbc[:], ALU.mult, [kpm, VK[2]], ['kkr'])
        k.tt('dve', sq[:], kkr[:], kkr[:], ALU.mult, ['kkr'], ['sq'])
        k.P.op('dve', lambda e: e.tensor_reduce(out=s4[:], in_=v3(sq[:]), axis=AX.X, op=ALU.add), reads=['sq'], writes=['s4'])
        k.act(s4[:], s4[:], AF.Sqrt, ['s4'], ['s4'])
        k.ts('dve', s4[:], s4[:], 1e-12, None, ALU.max, None, ['s4'], ['s4'])
        k.recip(rn[:], s4[:], ['s4'], ['rn'])
        k.ts('dve', rn[:], rn[:], -1.0, None, ALU.mult, None, ['rn'], ['rn'])
        k.tt('dve', v3(nkk[:]), v3(kkr[:]), bc4(rn[:]), ALU.mult, ['kkr', 'rn'], ['nkk'])
        k.stt(tmp[:], av[:], -1.0, kabc[:], ALU.add, ALU.mult, ['av', VK[3]], ['tmp'])
        k.stt(kmod[:], tmp[:], 1.0, k_, ALU.add, ALU.mult, ['tmp', kpm], ['kmod'])
        k.stt(kka[:], nkk[:], -1.0, av[:], ALU.mult, ALU.mult, ['nkk', 'av'], ['kka'])
        k.tt('dve', tmp[:], r_, kmod[:], ALU.mult, [kpm, 'kmod', 'tmp'], ['tmp'])
        k.tt('dve', tmp[:], tmp[:], rkbc[:], ALU.mult, ['tmp', VK[4]], ['tmp'])
        k.P.op('dve', lambda e: e.tensor_reduce(out=bon[:], in_=v3(tmp[:]), axis=AX.X, op=ALU.add), reads=['tmp'], writes=[kbon])
        yield 'sub'
        k.mm(B[S1][:, 0:W], triw[:, 0, :], sw[:], True, True, ['triw', 'sw'], [bk(S1)])
        k.mm(B[S0][:, 0:W], triw[:, 1, :], sw[:], True, True, ['triw', 'sw'], [bk(S0)])
        yield 'sub'
        k.act(E1[:], B[S1][:, 0:W], AF.Exp, [bk(S1)], ['E1'])
        k.act(E2[:], B[S1][:, 0:W], AF.Exp, [bk(S1)], ['E2'], scale=-1.0)
        k.act(E3[:], B[S0][:, 0:W], AF.Exp, [bk(S0)], ['E3'])
        k.mm(B[S1][:, 0:W], triw[:, 2, :], sw[:], True, True, ['triw', 'sw'], [bk(S1)])
        k.act(E4[:], B[S1][:, 0:W], AF.Exp, [bk(S1)], ['E4'])
        yield 'sub'
        for g in range(2):
            for hl in range(4):
                h = 4 * g + hl
                k.mm(B[S0 + g][0:64, hl * 128:(hl + 1) * 128], sw[:, h * 64:(h + 1) * 64], triw[:, 0, :], True, True,
                     ['sw', 'triw'], [bk(S0 + g)])
        yield 'sub'
        for g in range(2):
            k.act(E1T[:, 4 * g:4 * g + 4, :].rearrange("p a t -> p (a t)"), B[S0 + g][0:64, :], AF.Exp, [bk(S0 + g)], [kE1T])
        yield 'sub'
        k.tt('dve', At[:], nkk[:], E3[:], ALU.mult, ['nkk', 'E3'], [kAt])
        k.tt('dve', Bs[:], kka[:], E2[:], ALU.mult, ['kka', 'E2'], [kBs])
        k.tt('dve', Ks[:], kmod[:], E2[:], ALU.mult, ['kmod', 'E2'], [kKs])
        k.tt('dve', Rt[:], r_, E1[:], ALU.mult, [kpm, 'E1'], [kRt])
        for c in range(NCK):
            k.stt(Bf[c][:], kka[:], rowm[:, c:c + 1], E4[:], ALU.mult, ALU.mult, ['kka', 'E4', 'rowm'], [kBf])
            k.stt(Kf[c][:], kmod[:], rowm[:, c:c + 1], E4[:], ALU.mult, ALU.mult, ['kmod', 'E4', 'rowm'], [kKf])
        yield 'STAGE'
        for g in range(2):
            HS = list(range(4 * g, 4 * g + 4))
            for h in HS:
                hl = h % 4
                cs_ = slice(h * 64, (h + 1) * 64)
                for q, (src, key) in enumerate([(At, kAt), (Bs, kBs), (Ks, kKs), (Rt, kRt)]):
                    k.tr(B[hl][0:64, q * 128:(q + 1) * 128], src[:, cs_], k.identf[:], [key], [bk(hl)])
            for h in HS:
                hl = h % 4
                k.cp('act' if h % 2 else 'dve', FT[h][:].rearrange("p a t -> p (a t)"), B[hl][0:64, :], [bk(hl)], [f'FT{h}'])
            for h in HS:
                hl = h % 4
                AtT, BsT, KsT, RtT = (FT[h][:, q, :] for q in range(4))
                k.mm(B[hl][:, 0:128], BsT, AtT, True, True, [f'FT{h}'], [bk(hl)])
                k.mm(B[hl][:, 128:256], AtT, BsT, True, True, [f'FT{h}'], [bk(hl)])
                k.mm(B[hl][:, 256:384], KsT, AtT, True, True, [f'FT{h}'], [bk(hl)])
            for h in HS:
                hl = h % 4
                k.tt('dve', A5[h][:, 0:384], B[hl][:, 0:384], mask5[:, 0:384], ALU.mult, [bk(hl), 'mask5'], [f'A5_{h}'])
            for h in HS:
                hl = h % 4
                AtT, BsT, KsT, RtT = (FT[h][:, q, :] for q in range(4))
                k.mm(B[hl][:, 0:128], BsT, RtT, True, True, [f'FT{h}'], [bk(hl)])
                k.mm(B[hl][:, 128:256], KsT, RtT, True, True, [f'FT{h}'], [bk(hl)])
            for h in HS:
                hl = h % 4
                k.tt('dve', A5[h][:, 384:640], B[hl][:, 0:256], mask5[:, 384:640], ALU.mult, [bk(hl), 'mask5'], [f'A5b_{h}'])
                k.cp('act', NL[h][:], rd(A5[h][:, 0:256]), [f'A5_{h}'], [f'NL_{h}'])
                k.tt('dve', PQ[h][:, 0:128], rd(A5[h][:, 0:128]), k.identf[:], ALU.add, [f'A5_{h}', 'ident'], [f'PQ_{h}'])
            for lev in range(nlev):
                last = (lev == nlev - 1)
                for h in HS:
                    hl = h % 4
                    N_, L_ = NL[h][:, 0:128], NL[h][:, 128:256]
                    k.mm(B[hl][:, 0:128], L_, N_, True, True, [f'NL_{h}'], [bk(hl)])
                    k.mm(B[hl][:, 128:256], N_, L_, True, True, [f'NL_{h}'], [bk(hl)])
                for h in HS:
                    hl = h % 4
                    k.cp('act', NL[h][:], B[hl][:, 0:256], [bk(hl)], [f'NL_{h}'])
                for h in HS:
                    hl = h % 4
                    k.mm(B[hl][:, 256:384], NL[h][:, 128:256], PQ[h][:, 0:128], True, True, [f'NL_{h}', f'PQ_{h}'], [bk(hl)])
                for h in HS:
                    hl = h % 4
                    k.tt('dve', PQ[h][:, 0:128], B[hl][:, 256:384], rd(PQ[h][:, 0:128]), ALU.add, [bk(hl), f'PQ_{h}'], [f'PQ_{h}'])
            yield 'GROUP'
        for h in range(NH):
            k.mm(B[C0][:, h * 64:(h + 1) * 64], A5[h][:, 256:384], vr[:, h * 64:(h + 1) * 64], True, True, [f'A5_{h}', kvr], [bk(C0)])
        yield 'sub'
        k.cp('act', W1[:], B[C0][:, 0:W], [bk(C0)], ['W1'])
        yield 'sub'
        for h in range(NH):
            k.mm(B[C1][:, h * 64:(h + 1) * 64], PQ[h][:, 0:128], W1[:, h * 64:(h + 1) * 64], True, True,
                 [f'PQ_{h}', 'W1'], [bk(C1)])
        yield 'sub'
        k.cp('act', U1[:], B[C1][:, 0:W], [bk(C1)], ['U1'])
        yield 'sub'
        for c in range(NCK):
            cr = slice(c * CH, (c + 1) * CH)
            for h in range(NH):
                k.mm(B[C0][:, h * 64:(h + 1) * 64], FT[h][:, 0, :], ST[h][:], True, True, [f'FT{h}', f'ST{h}'], [bk(C0)])
            yield 'sub'
            k.cp('act', P1s[cr, :], B[C0][cr, 0:W], [bk(C0)], ['P1s'])
            yield 'sub'
            for h in range(NH):
                k.mm(B[C0][:, h * 64:(h + 1) * 64], PQ[h][:, :], P1s[:, h * 64:(h + 1) * 64], True, True,
                     [f'PQ_{h}', 'P1s'], [bk(C0)])
            yield 'sub'
            k.tt('dve', Us[cr, :], B[C0][cr, 0:W], U1[cr, :], ALU.add, [bk(C0), 'U1'], ['Us'])
            yield 'sub'
            for h in range(NH):
                hc_ = slice(h * 64, (h + 1) * 64)
                k.mm(B[C0][:, hc_], FT[h][:, 3, :], ST[h][:], True, False, [f'FT{h}', f'ST{h}'], [bk(C0)])
                k.mm(B[C0][:, hc_], A5[h][:, 384:512], Us[:, hc_], False, False, [f'A5b_{h}', 'Us'], [bk(C0)])
                k.mm(B[C0][:, hc_], A5[h][:, 512:640], vr[:, hc_], False, True, [f'A5b_{h}', kvr], [bk(C0)])
            yield 'sub'
            k.cp('act', ysb[cr, :], B[C0][cr, 0:W], [bk(C0)], ['ysb'])
            for h in range(NH):
                hc_ = slice(h * 64, (h + 1) * 64)
                k.mm(B[C1][0:64, hc_], Bf[c][:, hc_], rdc(Us[:, hc_]), True, False, [kBf, 'Us'], [bk(C1)])
                k.mm(B[C1][0:64, hc_], Kf[c][:, hc_], rd(vr[:, hc_]), False, True, [kKf, kvr], [bk(C1)])
            yield 'sub'
            for h in range(NH):
                hc_ = slice(h * 64, (h + 1) * 64)
                k.stt(ST[h][:], rdc(ST[h][:]), E1T[:, h, (c + 1) * CH - 1:(c + 1) * CH], B[C1][0:64, hc_], ALU.mult, ALU.add,
                      [f'ST{h}', kE1T, bk(C1)], [f'ST{h}'])
        k.P.op('dve', lambda e: e.tensor_reduce(out=m4[:], in_=v3(ysb[:]), axis=AX.X, op=ALU.add), reads=['ysb'], writes=['m4'])
        k.ts('dve', m4[:], m4[:], -1.0 / 64.0, None, ALU.mult, None, ['m4'], ['m4'])
        k.tt('dve', v3(yc[:]), v3(ysb[:]), bc4(m4[:]), ALU.add, ['ysb', 'm4'], ['yc'])
        k.tt('dve', sqp[:], yc[:], yc[:], ALU.mult, ['yc'], ['sqp'])
        k.P.op('dve', lambda e: e.tensor_reduce(out=r4[:], in_=v3(sqp[:]), axis=AX.X, op=ALU.add), reads=['sqp'], writes=['r4'])
        k.ts('dve', r4[:], r4[:], 1.0 / 64.0, GN_EPS, ALU.mult, ALU.add, ['r4'], ['r4'])
        k.act(r4[:], r4[:], AF.Sqrt, ['r4'], ['r4'])
        k.recip(r4[:], r4[:], ['r4'], ['r4'])
        k.tt('dve', v3(yc[:]), v3(yc[:]), bc4(r4[:]), ALU.mult, ['yc', 'r4'], ['yc'])
        k.tt('dve', yc[:], yc[:], lngbc[:], ALU.mult, ['yc', VK[5]], ['yc'])
        k.tt('dve', yc[:], yc[:], lnbbc[:], ALU.add, ['yc', VK[6]], ['yc'])
        k.tt('dve', v3(tmpp[:]), v3(rd(vr[:])), bc4(bon[:]), ALU.mult, [kvr, kbon], ['tmpp'])
        k.tt('dve', yc[:], yc[:], tmpp[:], ALU.add, ['yc', 'tmpp'], ['yc'])
        k.tt('dve', ot[b][:], yc[:], gv[:], ALU.mult, ['yc', kgv], [f'ot{b}'])
        k.dma('pool', oc[rows, :], ot[b][:], r=[f'ot{b}'], final=True)

    gens = {}
    done = set()

    def adv(j):
        try:
            return next(gens[j])
        except StopIteration:
            done.add(j)
            return 'END'

    for step in range(NT + 2):
        if step < NT:
            gens[step] = tile(step)
            while adv(step) != 'STAGE':
                pass
        jb = step - 2
        if 0 <= jb < NT:
            n_g = 0
            while n_g < 2:
                if adv(jb) == 'GROUP':
                    n_g += 1
        ja = step - 1
        a_live = 0 <= ja < NT
        b_live = 0 <= jb < NT
        while a_live or b_live:
            if a_live:
                if adv(ja) == 'STAGE':
                    a_live = False
            if b_live:
                if adv(jb) == 'END':
                    b_live = False
    return k.finish()


def rwkv_consts(CH=64):
    c = -math.exp(-0.5)
    blk = np.kron(np.eye(128 // CH), np.ones((CH, CH)))
    s_idx = np.arange(128)[:, None]
    t_idx = np.arange(128)[None, :]
    triw = np.stack([c * blk * (s_idx <= t_idx), c * blk * (s_idx < t_idx), c * blk * (s_idx > t_idx)]).astype(np.float32)
    lt_, le_, gt_ = blk * (s_idx < t_idx), blk * (s_idx <= t_idx), blk * (t_idx < s_idx)
    mask5 = np.concatenate([lt_, gt_, lt_, le_, le_], 1).astype(np.float32)
    rowm = np.stack([(np.arange(128) < 64), (np.arange(128) >= 64)], 1).astype(np.float32) if CH == 64 else np.ones((128, 2), np.float32)
    return dict(ident=np.eye(128, dtype=np.float32), triw=triw, mask5=mask5, rowm=rowm)


def rwkv_host_inputs(s, p_rwkv, prm, NH=4, CH=64):
    L = p_rwkv.shape[0]
    cs = slice(64 * NH * s, 64 * NH * (s + 1))
    r_, w1, k_, v_, a1, g1 = np.split(p_rwkv, np.cumsum([512, 64, 512, 512, 64])[:5], axis=-1)
    mu = prm['rwkv_mu']
    mur, muw1, muk, muv, mua1, mug1 = np.split(mu, np.cumsum([512, 64, 512, 512, 64])[:5])
    zm = np.zeros(64, np.float32)
    mul = np.concatenate([muw1, zm, mua1, zm, mug1]).reshape(3, 128).T
    vecs = np.stack([prm['rwkv_w0'][cs], prm['rwkv_a0'][cs], prm['rwkv_k_k'][cs], prm['rwkv_k_a'][cs],
                     prm['rwkv_r_k'].reshape(-1)[cs], prm['rwkv_ln_gain'][cs], prm['rwkv_ln_bias'][cs]])
    c_ = np.ascontiguousarray
    d = dict(pr=c_(r_[:, cs]), pk=c_(k_[:, cs]), pv=c_(v_[:, cs]),
             mu1=c_(np.concatenate([mur[cs], muk[cs], muv[cs]])),
             plw=c_(w1.T), pla=c_(a1.T), plg=c_(g1.T), mul=c_(mul),
             w2=c_(prm['rwkv_w2'][:, cs]), a2=c_(prm['rwkv_a2'][:, cs]),
             g2=c_(prm['rwkv_g2'][:, cs]), vecs=c_(vecs))
    d.update(rwkv_consts(CH))
    return d


FM0 = [(0, 128, 0), (128, 128, 128), (256, 128, 256), (384, 128, 384), (1536, 16, 512)] + \
      [(1552 + j * 128, 128, 528 + j * 128) for j in range(4)]
NF0 = 1040
FM1 = [(512, 64, 0), (1600, 64, 64), (1664, 128, 128)] + [(1792 + j * 128, 128, 256 + j * 128) for j in range(8)]
NF1 = 1280


def host_params(inp):
    c_ = lambda a: np.ascontiguousarray(np.asarray(a), dtype=np.float32)
    P = {}
    P['ident'] = np.eye(128, dtype=np.float32)
    P['triu'] = np.triu(np.ones((128, 128), np.float32))
    P['trigt'] = np.tril(np.ones((128, 128), np.float32), -1)
    for l in range(2):
        for j in range(7):
            P[f'g{l}_{j}'] = c_(inp['norm_gain'][l][j])
        for nm in ('xa_wq', 'xa_wk', 'xa_wv', 'xa_wo', 'mlp_w1', 'mlp_w2'):
            P[f'{nm}{l}'] = c_(inp[nm][l])
    P['w_in0'] = c_(inp['ab_w_in'][0])
    P['w_in1'] = c_(inp['cd_w_in'][0])
    P['w_out0'] = c_(inp['ab_w_out'][0])
    P['w_out1'] = c_(inp['cd_w_out'][0])
    P['wglu'] = c_(inp['s5_w_glu'][0])
    P['bglu'] = c_(inp['s5_b_glu'][0])
    prm0 = {k_: np.asarray(inp[k_][0]) for k_ in inp if k_.startswith('s5_') or k_.startswith('gla_')}
    prm1 = {k_: np.asarray(inp[k_][0]) for k_ in inp if k_.startswith('rwkv_') or k_.startswith('lru_')}
    for s in range(2):
        cs = slice(s * 128, (s + 1) * 128)
        P[f'gla_w2_{s}'] = c_(prm0['gla_w_decay2'][:, cs])
        P[f'gla_bd_{s}'] = c_(prm0['gla_b_decay'][None, cs])
        P[f'gla_gn_{s}'] = c_(prm0['gla_norm_gain'][2 * s:2 * s + 2].reshape(256))
        d = s5_host_inputs(s, np.zeros((2, 512), np.float32), prm0)
        for nm in ('lam_re', 'lam_im', 'lstep', 'Bre', 'Bim', 'Cre', 'Cim', 'dsk'):
            P[f's5_{nm}_{s}'] = c_(d[nm])
        P['iota_p'] = c_(d['iota_p'])
        P['iota_f'] = c_(d['iota_f'])
        if s == 0:
            d = rwkv_host_inputs(0, np.zeros((2, 1792), np.float32), prm1, 8, 64)
            for nm in ('mu1', 'mul', 'w2', 'a2', 'g2', 'vecs'):
                P[f'rw_{nm}'] = c_(d[nm])
            for nm in ('triw', 'mask5', 'rowm'):
                P[f'rw_{nm}'] = c_(d[nm])
        d = lru_host_inputs(s, np.zeros((2, 512), np.float32), np.zeros((2, 512), np.float32), prm1)
        for nm in ('cw', 'cb', 'Wa', 'Wx', 'ba', 'bx', 'lam'):
            P[f'lru_{nm}_{s}'] = c_(d[nm])
    return P


def build_fused(P, L):
    k = K(fused=True)
    X = {nm: k.xin(nm, a.shape) for nm, a in P.items()}
    x = k.xin('x', [L, D])
    mem = k.xin('mem', [256, D])
    out = k.xout('out', [L, D])
    proj0 = k.scratch('proj0', [L, 2064])
    PT0 = k.scratch('PT0', [NF0, L])
    proj1 = k.scratch('proj1', [L, 2816])
    PT1 = k.scratch('PT1', [NF1, L])
    o = k.scratch('o', [L, D])
    odT = k.scratch('odT', [512, L])
    h1 = k.scratch('h1', [L, D])
    h2 = k.scratch('h2', [L, D])
    h3 = k.scratch('h3', [L, D])

    def cblock(l, hin, hout, glu, ob_fm):
        io = dict(oa=o[:, 0:512], hin=hin, wout=X[f'w_out{l}'], g1=X[f'g{l}_1'], ident=X['ident'], hout=h1)
        if ob_fm:
            io['obT'] = odT
        else:
            io['ob'] = o[:, 512:1024]
        if glu:
            io.update(wglu=X['wglu'], bglu=X['bglu'])
        k.begin_phase(f'C1_{l}', io)
        build_C1(L, glu, k=k, ob_fm=ob_fm)
        k.begin_phase(f'C2_{l}', dict(hin=h1, mem=mem, wq=X[f'xa_wq{l}'], wk=X[f'xa_wk{l}'], wv=X[f'xa_wv{l}'], wo=X[f'xa_wo{l}'],
                                      g2=X[f'g{l}_2'], g3=X[f'g{l}_3'], g6=X[f'g{l}_6'], ident=X['ident'], hout=h2))
        build_C2(L, k=k)
        k.begin_phase(f'C3_{l}', dict(hin=h2, w1=X[f'mlp_w1{l}'], w2=X[f'mlp_w2{l}'], g4=X[f'g{l}_4'], g5=X[f'g{l}_5'],
                                      ident=X['ident'], hout=hout))
        build_C3(L, k=k)

    k.begin_phase('A0', dict(x=x, gain=X['g0_0'], W=X['w_in0'], ident=X['ident'], out=proj0, outT=PT0))
    build_A2(L, 2064, FM0, NF0, k=k)
    for s in range(2):
        io_g = dict(qT=PT0[s * 128:(s + 1) * 128, :], kT=PT0[256 + s * 128:256 + (s + 1) * 128, :],
                    ktok=proj0[:, 256 + s * 128:256 + (s + 1) * 128], v=proj0[:, 512 + s * 256:512 + (s + 1) * 256],
                    gate=proj0[:, 1024 + s * 256:1024 + (s + 1) * 256], dlrT=PT0[512:528, :],
                    w2=X[f'gla_w2_{s}'], bdec=X[f'gla_bd_{s}'], gn=X[f'gla_gn_{s}'], triu=X['triu'],
                    trigt=X['trigt'], oa=o[:, s * 256:(s + 1) * 256])
        k.begin_phase(f'GLA{s}', io_g)
        build_GLA(L, k=k)
    for s in range(2):
        io_s = dict(uT=PT0[528 + s * 256:528 + (s + 1) * 256, :], u=proj0[:, 1552 + s * 256:1552 + (s + 1) * 256],
                    triu=X['triu'], iota_p=X['iota_p'], iota_f=X['iota_f'], y=o[:, 512 + s * 256:512 + (s + 1) * 256])
        for nm in ('lam_re', 'lam_im', 'lstep', 'Bre', 'Bim', 'Cre', 'Cim', 'dsk'):
            io_s[nm] = X[f's5_{nm}_{s}']
        k.begin_phase(f'S5{s}', io_s)
        build_S5(L, k=k)
    cblock(0, x, h3, True, False)
    k.begin_phase('A1', dict(x=h3, gain=X['g1_0'], W=X['w_in1'], ident=X['ident'], out=proj1, outT=PT1))
    build_A2(L, 2816, FM1, NF1, k=k)
    io = dict(pr=proj1[:, 0:512], pk=proj1[:, 576:1088], pv=proj1[:, 1088:1600], plw=PT1[0:64, :], pla=PT1[64:128, :],
              plg=PT1[128:256, :], ident=X['ident'], triw=X['rw_triw'], mask5=X['rw_mask5'], rowm=X['rw_rowm'], oc=o[:, 0:512])
    for nm in ('mu1', 'mul', 'w2', 'a2', 'g2', 'vecs'):
        io[nm] = X[f'rw_{nm}']
    k.begin_phase('RW', io)
    build_RWKVP(L, k=k, CH=64)
    streams = []
    for s in range(2):
        io = dict(xbT=PT1[256 + s * 256:256 + (s + 1) * 256, :], gateT=PT1[768 + s * 256:768 + (s + 1) * 256, :],
                  odT=odT[s * 256:(s + 1) * 256, :])
        for nm in ('cw', 'cb', 'Wa', 'Wx', 'ba', 'bx', 'lam'):
            io[nm] = X[f'lru_{nm}_{s}']
        streams.append((f'l{s}_', io, lambda kk: gen_LRU(L, kk)))
    k.begin_phase('LRU', {})
    run_streams(k, streams)
    k.finish()
    cblock(1, h3, out, False, True)
    return k.finish_program()


BATCH, SEQ = 4, 4096
_CACHE = {}


def kernel(**inp):
    inp = {k_: np.asarray(v_) for k_, v_ in inp.items()}
    P = host_params(inp)
    if 'nc' not in _CACHE:
        _CACHE['nc'] = build_fused(P, SEQ)
    nc = _CACHE['nc']
    maps = []
    for b in range(BATCH):
        m = dict(P)
        m['x'] = np.ascontiguousarray(inp['x'][b], dtype=np.float32)
        m['mem'] = np.ascontiguousarray(inp['mem'][b], dtype=np.float32)
        maps.append(m)
    res = run_bass_kernel_spmd(nc, maps, core_ids=list(range(BATCH))).results
    return np.ascontiguousarray(np.stack([res[b]['out'] for b in range(BATCH)]).astype(np.float32))
```

```python
import os
import math
from contextlib import ExitStack


import numpy as np
import concourse.bass as bass
import concourse.mybir as mybir
from concourse.bass_utils import run_bass_kernel_spmd

F32 = mybir.dt.float32
BF16 = mybir.dt.bfloat16
I32 = mybir.dt.int32
AF = mybir.ActivationFunctionType
ALU = mybir.AluOpType
AX = mybir.AxisListType

ENGS = ['pe', 'act', 'dve', 'pool', 'sp']
NDMA_SLOTS = 8
SAME_ENGINE_SYNC = os.environ.get("NOSELF", "0") != "1"


class Prog:
    def __init__(self, nc):
        self.nc = nc
        self.ops = {e: [] for e in ENGS}
        self.cnt = {e: 0 for e in ENGS}
        self.last_w = {}
        self.readers = {}
        self.seen = {e: {} for e in ENGS}
        self.dma_n = {e: 0 for e in ENGS}
        self.dma_tok = {e: [None] * NDMA_SLOTS for e in ENGS}
        self.final_tokens = []
        from contextlib import ExitStack
        self.sem_stack = ExitStack()
        self.sems = {}
        for e in ['pe', 'act', 'dve', 'pool']:
            self.sems[('c', e)] = self.sem_stack.enter_context(nc.semaphore("s_c_" + e))
        for q in ['sp', 'pool']:
            for sl in range(NDMA_SLOTS):
                self.sems[('d', q, sl)] = self.sem_stack.enter_context(nc.semaphore(f"s_d_{q}_{sl}"))

    def barrier(self):
        toks = []
        for e in ['pe', 'act', 'dve', 'pool']:
            if self.cnt[e] > 0:
                toks.append((('c', e), self.cnt[e]))
        for q in ENGS:
            for t in self.dma_tok[q]:
                if t is not None:
                    toks.append(t)
        for e in ENGS:
            waits = []
            for (sem, val) in toks:
                if sem == ('c', e):
                    continue
                if self.seen[e].get(sem, 0) >= val:
                    continue
                waits.append((sem, val))
                self.seen[e][sem] = val
            if waits:
                self.ops[e].append((waits, None, None))
        self.last_w = {}
        self.readers = {}

    def _deps(self, eng, reads, writes):
        toks = []
        for r in reads:
            t = self.last_w.get(r)
            if t is not None:
                toks.append(t)
        for w in writes:
            t = self.last_w.get(w)
            if t is not None:
                toks.append(t)
            toks.extend(self.readers.get(w, []))
        need = {}
        for (sem, val) in toks:
            if not SAME_ENGINE_SYNC and sem == ('c', eng):
                continue
            if sem == ('c', 'pe') and eng == 'pe':
                continue
            if self.seen[eng].get(sem, 0) >= val:
                continue
            if need.get(sem, 0) < val:
                need[sem] = val
        for sem, val in need.items():
            self.seen[eng][sem] = val
        return list(need.items())

    def _commit(self, tok, reads, writes):
        for w in writes:
            self.last_w[w] = tok
            self.readers[w] = []
        for r in reads:
            if r in writes:
                continue
            self.readers.setdefault(r, []).append(tok)

    def op(self, eng, fn, reads=(), writes=()):
        self.nrec = getattr(self, 'nrec', 0) + 1
        if self.nrec > int(os.environ.get("MAXOPS", "100000000")):
            return None
        kp = getattr(self, 'key_prefix', '')
        reads = [r if r.startswith('ps') else kp + r for r in reads]
        writes = [w if w.startswith('ps') else kp + w for w in writes]
        pk = getattr(self, 'ps_prefix', '')
        reads = [('ps' + pk + r[2:]) if r.startswith('ps') else r for r in reads]
        writes = [('ps' + pk + w[2:]) if w.startswith('ps') else w for w in writes]
        writes = list(writes) + [r for r in reads if r.startswith('ps') and r not in writes]
        waits = self._deps(eng, reads, writes)
        self.cnt[eng] += 1
        tok = (('c', eng), self.cnt[eng])
        self.ops[eng].append((waits, fn, tok))
        self._commit(tok, reads, writes)
        return tok

    def dma(self, q, out, in_, reads=(), writes=(), final=False, **kw):
        self.nrec = getattr(self, 'nrec', 0) + 1
        if self.nrec > int(os.environ.get("MAXOPS", "100000000")):
            return None
        kp = getattr(self, 'key_prefix', '')
        reads = [kp + r for r in reads]
        writes = [kp + w for w in writes]
        waits = self._deps(q, reads, writes)
        n = self.dma_n[q]
        slot = n % NDMA_SLOTS
        prev = self.dma_tok[q][slot]
        if prev is not None and self.seen[q].get(prev[0], 0) < prev[1]:
            waits.append(prev)
            self.seen[q][prev[0]] = prev[1]
        tok = (('d', q, slot), 16 * (n // NDMA_SLOTS + 1))
        self.dma_n[q] += 1
        self.dma_tok[q][slot] = tok

        def fn(e, out=out, in_=in_, kw=kw):
            return e.dma_start(out=out, in_=in_, **kw)
        self.ops[q].append((waits, fn, tok))
        self._commit(tok, reads, writes)
        if final:
            self.final_tokens.append(tok)
        return tok

    def emit(self, last=True):
        nc = self.nc
        sems = self.sems
        with nc.Block() as block:
            final = list(self.final_tokens) if last else []

            def run(e, name):
                for waits, fn, tok in self.ops[name]:
                    for (s, v) in waits:
                        e.wait_ge(sems[s], v)
                    if fn is None:
                        continue
                    inst = fn(e)
                    inc = 16 if tok[0][0] == 'd' else 1
                    inst.then_inc(sems[tok[0]], inc)
                if name == 'sp':
                    for (s, v) in final:
                        e.wait_ge(sems[s], v)
                self.ops[name] = []

            @block.tensor
            def _(e):
                run(e, 'pe')

            @block.scalar
            def _(e):
                run(e, 'act')

            @block.vector
            def _(e):
                run(e, 'dve')

            @block.gpsimd
            def _(e):
                run(e, 'pool')

            @block.sync
            def _(e):
                run(e, 'sp')
        if last:
            self.sem_stack.close()


D = 1024
KC = 8
EPS = 1e-6


class K:
    def __init__(self, fused=False):
        self.nc = bass.Bass("TRN2", target_bir_lowering=False)
        self.st = ExitStack()
        self.P = Prog(self.nc)
        self.n = 0
        self.fused = fused
        self.io = {}
        self.pfx = ""

    def begin_phase(self, name, io):
        self.pfx = name + "_"
        self.io = io
        self.st = ExitStack()
        for a in ('wstage', 'rr_cache', 'identf', 'identb'):
            if hasattr(self, a):
                delattr(self, a)

    def scratch(self, name, shape, dt=F32):
        return self.nc.dram_tensor(name, list(shape), dt, kind="Internal").ap()

    def xin(self, name, arr_shape, dt=F32):
        return self.nc.dram_tensor(name, list(arr_shape), dt, kind="ExternalInput").ap()

    def xout(self, name, arr_shape, dt=F32):
        return self.nc.dram_tensor(name, list(arr_shape), dt, kind="ExternalOutput").ap()

    def din(self, name, shape, dt=F32):
        if self.fused:
            ap = self.io[name]
            assert list(ap.shape) == list(shape), (name, ap.shape, shape)
            return ap
        return self.nc.dram_tensor(name, list(shape), dt, kind="ExternalInput").ap()

    def dout(self, name, shape, dt=F32):
        if self.fused:
            ap = self.io[name]
            assert list(ap.shape) == list(shape), (name, ap.shape, shape)
            return ap
        return self.nc.dram_tensor(name, list(shape), dt, kind="ExternalOutput").ap()

    def sb(self, name, shape, dt=F32):
        pers = getattr(self, 'persist', None)
        if pers is not None and (self.pfx + name) in pers:
            return pers[self.pfx + name]
        return self.st.enter_context(self.nc.sbuf_tensor(self.pfx + name, list(shape), dt))

    def push_scope(self, persistent):
        self.persist = getattr(self, 'persist', None) or {}
        for (name, shape, dt) in persistent:
            self.persist[self.pfx + name] = self.st.enter_context(self.nc.sbuf_tensor(self.pfx + name, list(shape), dt))
        self._st_saved = self.st
        self.st = ExitStack()

    def pop_scope(self):
        self.P.barrier()
        self.P.emit(last=False)
        self.st.close()
        self.st = self._st_saved

    def ps(self, name, shape, dt=F32):
        return self.st.enter_context(self.nc.psum_tensor(self.pfx + name, list(shape), dt))

    def finish(self, last=True):
        if self.fused:
            self.P.barrier()
            self.P.emit(last=False)
            self.st.close()
            return None
        self.P.emit()
        self.st.close()
        return self.nc

    def finish_program(self):
        self.P.emit(last=True)
        return self.nc

    def mm(self, out, lhsT, rhs, start, stop, r, w):
        self.P.op('pe', lambda e: e.matmul(out, lhsT=lhsT, rhs=rhs, start=start, stop=stop), reads=r, writes=w)

    def tr(self, out, in_, ident, r, w):
        self.P.op('pe', lambda e: e.transpose(out=out, in_=in_, identity=ident), reads=list(r) + ['ident'], writes=w)

    def act(self, out, in_, func, r, w, **kw):
        self.P.op('act', lambda e: e.activation(out=out, in_=in_, func=func, **kw), reads=r, writes=w)

    def tt(self, eng, out, in0, in1, op, r, w):
        self.P.op(eng, lambda e: e.tensor_tensor(out=out, in0=in0, in1=in1, op=op), reads=r, writes=w)

    def ts(self, eng, out, in0, s1, s2, op0, op1, r, w):
        if op1 is None:
            self.P.op(eng, lambda e: e.tensor_scalar(out=out, in0=in0, scalar1=s1, scalar2=None, op0=op0), reads=r, writes=w)
        else:
            self.P.op(eng, lambda e: e.tensor_scalar(out=out, in0=in0, scalar1=s1, scalar2=s2, op0=op0, op1=op1), reads=r, writes=w)

    def stt(self, out, in0, scalar, in1, op0, op1, r, w):
        self.P.op('dve', lambda e: e.scalar_tensor_tensor(out=out, in0=in0, scalar=scalar, in1=in1, op0=op0, op1=op1),
                  reads=r, writes=w)

    def cp(self, eng, out, in_, r, w):
        if eng == 'act':
            self.P.op('act', lambda e: e.copy(out=out, in_=in_), reads=r, writes=w)
        else:
            self.P.op(eng, lambda e: e.tensor_copy(out=out, in_=in_), reads=r, writes=w)

    def recip(self, out, in_, r, w):
        self.P.op('dve', lambda e: e.reciprocal(out=out, in_=in_), reads=r, writes=w)

    def memset(self, eng, ap, val, w):
        self.P.op(eng, lambda e: e.memset(ap, val), reads=[], writes=w)

    def dma(self, q, out, in_, r=(), w=(), final=False, **kw):
        self.P.dma(q, out, in_, reads=r, writes=w, final=final, **kw)

    def consts(self, ident_d):
        self.identf = self.sb("identf", [128, 128], F32)
        self.identb = self.sb("identb", [128, 128], BF16)
        self.dma('sp', self.identf[:], ident_d, w=['ident'])
        self.cp('dve', self.identb[:], self.identf[:], ['ident'], ['ident'])

    def gain_cols(self, name, g_d):
        t = self.sb(name, [128, KC], F32)
        self.dma('sp', t[:], g_d.rearrange("(kc p) -> p kc", p=128), w=[name], allow_slow_non_contiguous=True)
        return t

    def bcast_row(self, name, vec_d, n):
        t = self.sb(name, [128, n], F32)
        self.dma('sp', t[:], vec_d.partition_broadcast(128), w=[name])
        return t

    def load_weight(self, name, w_d, kchunks, ncols, gcol=None, gkey=None, stage_cols=2048, q='sp'):
        wb = self.sb(name, [128, kchunks, ncols], BF16)
        if not hasattr(self, 'wstage'):
            self.wstage = [self.sb(f"wstage{i}", [128, stage_cols], F32) for i in range(2)]
            self.wstage_n = 0
            self.wstage_cols = stage_cols
        sc = self.wstage_cols
        wv = w_d.rearrange("(kc p) n -> p kc n", p=128)
        for kc in range(kchunks):
            for c0 in range(0, ncols, sc):
                cw = min(sc, ncols - c0)
                b = self.wstage_n % 2
                self.wstage_n += 1
                stg = self.wstage[b]
                self.dma(q, stg[:, 0:cw], wv[:, kc, c0:c0 + cw], w=[f'wstage{b}'])
                eng = 'act' if (kc % 2 == 0) else 'dve'
                if gcol is not None:
                    if eng == 'act':
                        self.act(wb[:, kc, c0:c0 + cw], stg[:, 0:cw], AF.Copy, [f'wstage{b}', gkey], [f'{name}{kc}'],
                                 scale=gcol[:, kc:kc + 1])
                    else:
                        self.ts('dve', wb[:, kc, c0:c0 + cw], stg[:, 0:cw], gcol[:, kc:kc + 1], None, ALU.mult, None,
                                [f'wstage{b}', gkey], [f'{name}{kc}'])
                else:
                    self.cp(eng, wb[:, kc, c0:c0 + cw], stg[:, 0:cw], [f'wstage{b}'], [f'{name}{kc}'])
        return wb

    def rstd_of(self, x_ap, xkey, ss, rstd, junk, key, ncols=D):
        self.act(junk, x_ap, AF.Square, [xkey], ['junk', key + 'ss'], accum_out=ss)
        self.ts('dve', rstd, ss, 1.0 / ncols, EPS, ALU.mult, ALU.add, [key + 'ss'], [key])
        self.act(rstd, rstd, AF.Sqrt, [key], [key])
        self.recip(rstd, rstd, [key], [key])


def pipeline(make_gen, n):
    active = []
    for i in range(n):
        for g in list(active):
            try:
                next(g)
            except StopIteration:
                active.remove(g)
        g = make_gen(i)
        active.append(g)
        try:
            next(g)
        except StopIteration:
            active.remove(g)
    while active:
        for g in list(active):
            try:
                next(g)
            except StopIteration:
                active.remove(g)


def pipeline_gen(make_gen, n):
    active = []
    for i in range(n):
        for g in list(active):
            try:
                next(g)
            except StopIteration:
                active.remove(g)
        g = make_gen(i)
        active.append(g)
        try:
            next(g)
        except StopIteration:
            active.remove(g)
        yield
    while active:
        for g in list(active):
            try:
                next(g)
            except StopIteration:
                active.remove(g)
        yield


def run_streams(k, streams):
    base_pfx = k.pfx
    gens = []
    for (pf, io, gf) in streams:
        gens.append([pf, io, None, gf])
    active = list(gens)
    while active:
        for st in list(active):
            pf, io, g, gf = st
            k.pfx = base_pfx + pf
            k.P.key_prefix = pf
            k.P.ps_prefix = pf
            k.io = io
            try:
                if g is None:
                    st[2] = gf(k)
                    g = st[2]
                next(g)
            except StopIteration:
                active.remove(st)
    k.pfx = base_pfx
    k.P.key_prefix = ''
    k.P.ps_prefix = ''


GELU_C = 1.5957691216057308


def norm_T(k, xt, xkey, xn, xnkey, xT_dst, xTkey, psT, psTkey, ss, rstd, junk, key, evac_eng='act'):
    k.rstd_of(xt, xkey, ss, rstd, junk, key)
    k.ts('dve', xn, xt, rstd, None, ALU.mult, None, [xkey, key], [xnkey])
    for kc in range(KC):
        k.tr(psT[:, kc * 128:(kc + 1) * 128], xn[:, kc * 128:(kc + 1) * 128], k.identb[:], [xnkey], [psTkey])
    k.cp(evac_eng, xT_dst, psT[:].rearrange("p (k t) -> p k t", k=KC), [psTkey], [xTkey])


def post_norm_res(k, ps2, pskeys, ht, hkey, gbc, gkey, tmp2, tmpkeys, ss2, rstd, junk, key):
    for j in range(2):
        k.act(junk[:, 0:512], ps2[j], AF.Square, [pskeys[j]], ['junk', key + f'ss{j}'], accum_out=ss2[:, j:j + 1])
    k.tt('dve', ss2[:, 0:1], ss2[:, 0:1], ss2[:, 1:2], ALU.add, [key + 'ss0', key + 'ss1'], [key + 'ss0'])
    k.ts('dve', rstd, ss2[:, 0:1], 1.0 / D, EPS, ALU.mult, ALU.add, [key + 'ss0'], [key])
    k.act(rstd, rstd, AF.Sqrt, [key], [key])
    k.recip(rstd, rstd, [key], [key])
    for j in range(2):
        sl = slice(j * 512, (j + 1) * 512)
        k.stt(tmp2[j], ps2[j], rstd, gbc[:, sl], ALU.mult, ALU.mult, [pskeys[j], key, gkey], [tmpkeys[j]])
        k.tt('pool', ht[:, sl], ht[:, sl], tmp2[j], ALU.add, [tmpkeys[j], hkey], [hkey])


def build_C1(NTOK, glu, k=None, ob_fm=False):
    k = k or K()
    NT = NTOK // 128
    oa = k.din("oa", [NTOK, 512])
    if ob_fm:
        obT = k.din("obT", [512, NTOK])
    else:
        ob = k.din("ob", [NTOK, 512])
    hin = k.din("hin", [NTOK, D])
    wout = k.din("wout", [D, D])
    g1 = k.din("g1", [D])
    ident_d = k.din("ident", [128, 128])
    if glu:
        wglu = k.din("wglu", [512, 512])
        bglu = k.din("bglu", [512])
    hout = k.dout("hout", [NTOK, D])
    k.consts(ident_d)
    g1bc = k.bcast_row("g1bc", g1, D)
    Wout = k.load_weight("Wout", wout, KC, D, stage_cols=1024)
    if glu:
        Wglu = k.load_weight("Wglu", wglu, 4, 512)
        bgbc = k.bcast_row("bgbc", bglu, 512)

    def ring(nm, shape, n, dt=F32):
        return [k.sb(f"{nm}{j}", shape, dt) for j in range(n)]
    oc = ring("oc", [128, D], 10 if glu else 4)
    ocb = ring("ocb", [128, D], 3, BF16)
    oT = ring("oT", [128, KC, 128], 3, BF16)
    ht = ring("ht", [128, D], 4)
    mix = ring("mix", [128, D], 5)
    tmp = ring("tmp", [128, D], 3)
    ss2 = ring("ss2", [128, 2], 4)
    rstd = ring("rstd", [128, 1], 5)
    junk = k.sb("junk", [128, D], BF16)
    if ob_fm:
        obt = ring("obt", [128, 4, 128], 4)
    if glu:
        yb = ring("yb", [128, 512], 3, BF16)
        yT = ring("yT", [128, 4, 128], 3, BF16)
        t1 = ring("t1", [128, 512], 9)
        zs = ring("zs", [128, 512], 4)
        psTg = k.ps("psTg", [128, D], BF16)
        psG = k.ps("psG", [128, 512])
    psTm = [k.ps(f"psTm{j}", [128, D], BF16) for j in range(2)]
    psM = [k.ps(f"psM{j}", [128, 512]) for j in range(4)]

    def tile(i):
        rows = slice(i * 128, (i + 1) * 128)
        def T(lst, nm):
            j = i % len(lst)
            return lst[j], f'{nm}{j}'
        oc_, koc = T(oc, 'oc'); ocb_, kocb = T(ocb, 'ocb'); oT_, koT = T(oT, 'oT'); ht_, kht = T(ht, 'ht')
        mix_, kmix = T(mix, 'mix'); tmp_, ktmp = T(tmp, 'tmp'); ss_, kss = T(ss2, 'ss2'); rs_, krs = T(rstd, 'rstd')
        pm = [psM[2 * (i % 2)], psM[2 * (i % 2) + 1]]
        kpm = [f'psM{2 * (i % 2)}', f'psM{2 * (i % 2) + 1}']
        ptm, kptm = psTm[i % 2], f'psTm{i % 2}'
        kA, kB = koc + 'A', koc + 'B'
        k.dma('sp', oc_[:, 0:512], oa[rows, :], w=[kA])
        if ob_fm:
            obt_, kobt = T(obt, 'obt')
            k.dma('sp', obt_[:], obT[:, rows].rearrange("(a p) t -> p a t", p=128), w=[kobt])
        else:
            k.dma('sp', oc_[:, 512:1024], ob[rows, :], w=[kB])
        yield
        if glu:
            y = oc_[:, 512:1024]
            yb_, kyb = T(yb, 'yb'); yT_, kyT = T(yT, 'yT'); t1_, kt1 = T(t1, 't1'); zs_, kzs = T(zs, 'zs')
            k.cp('dve', yb_[:], y, [kB], [kyb])
            k.act(t1_[:], y, AF.Square, [kB], [kt1])
            k.act(t1_[:], t1_[:], AF.Copy, [kt1], [kt1], scale=0.044715, bias=1.0)
            yield
            for kc in range(4):
                k.tr(psTg[:, kc * 128:(kc + 1) * 128], yb_[:, kc * 128:(kc + 1) * 128], k.identb[:], [kyb], ['psTg'])
            k.tt('pool', t1_[:], t1_[:], y, ALU.mult, [kt1, kB], [kt1])
            yield
            k.cp('act', yT_[:], psTg[:, 0:512].rearrange("p (k t) -> p k t", k=4), ['psTg'], [kyT])
            k.act(t1_[:], t1_[:], AF.Sigmoid, [kt1], [kt1], scale=GELU_C)
            yield
            for kc in range(4):
                k.mm(psG[:], yT_[:, kc, :], Wglu[:, kc, :], kc == 0, kc == 3, [kyT, f'Wglu{kc}'], ['psG'])
            yield
            k.tt('dve', zs_[:], psG[:], bgbc[:], ALU.add, ['psG', 'bgbc'], [kzs])
            yield
            k.act(zs_[:], zs_[:], AF.Sigmoid, [kzs], [kzs])
            yield
            k.tt('dve', zs_[:], t1_[:], zs_[:], ALU.mult, [kt1, kzs], [kzs])
            k.tt('dve', y, y, zs_[:], ALU.mult, [kB, kzs], [kB])
        if ob_fm:
            k.cp('dve', ocb_[:, 0:512], oc_[:, 0:512], [kA], [kocb])
            k.cp('pool', oT_[:, 4:8, :], obt_[:], [kobt], [koT + 'b'])
        else:
            k.cp('dve', ocb_[:], oc_[:], [kA, kB], [kocb])
        yield
        nk = 4 if ob_fm else KC
        for kc in range(nk):
            k.tr(ptm[:, kc * 128:(kc + 1) * 128], ocb_[:, kc * 128:(kc + 1) * 128], k.identb[:], [kocb], [kptm])
        yield
        k.cp('act', oT_[:, 0:nk, :], ptm[:, 0:nk * 128].rearrange("p (k t) -> p k t", k=nk), [kptm], [koT])
        yield
        for cg in range(2):
            for kc in range(KC):
                ok_ = (koT + 'b') if (ob_fm and kc >= 4) else koT
                k.mm(pm[cg][:], oT_[:, kc, :], Wout[:, kc, cg * 512:(cg + 1) * 512], kc == 0, kc == KC - 1,
                     [ok_, f'Wout{kc}'], [kpm[cg]])
        yield
        for j in range(2):
            k.act(junk[:, 0:512], pm[j][:], AF.Square, [kpm[j]], ['junk', kss], accum_out=ss_[:, j:j + 1])
        for j in range(2):
            k.cp('act', mix_[:, j * 512:(j + 1) * 512], pm[j][:], [kpm[j]], [kmix])
        k.dma('sp', ht_[:], hin[rows, :], w=[kht])
        yield
        k.tt('dve', ss_[:, 0:1], ss_[:, 0:1], ss_[:, 1:2], ALU.add, [kss], [kss])
        k.ts('dve', rs_[:], ss_[:, 0:1], 1.0 / D, EPS, ALU.mult, ALU.add, [kss], [krs])
        yield
        k.act(rs_[:], rs_[:], AF.Sqrt, [krs], [krs])
        yield
        k.recip(rs_[:], rs_[:], [krs], [krs])
        k.stt(tmp_[:], mix_[:], rs_[:], g1bc[:], ALU.mult, ALU.mult, [kmix, krs, 'g1bc'], [ktmp])
        yield
        k.tt('pool', ht_[:], ht_[:], tmp_[:], ALU.add, [kht, ktmp], [kht])
        k.dma('pool', hout[rows, :], ht_[:], r=[kht], final=True)

    pipeline(tile, NT)
    return k.finish()


def build_C3(NTOK, k=None):
    k = k or K()
    NB = NTOK // 512
    DFF = 4096
    FC = DFF // 128
    hin = k.din("hin", [NTOK, D])
    w1 = k.din("w1", [D, DFF])
    w2 = k.din("w2", [DFF, D])
    g4 = k.din("g4", [D])
    g5 = k.din("g5", [D])
    ident_d = k.din("ident", [128, 128])
    hout = k.dout("hout", [NTOK, D])
    k.consts(ident_d)
    g4c = k.gain_cols("g4c", g4)
    g5bc = k.bcast_row("g5bc", g5, D)
    W1 = k.load_weight("W1", w1, KC, DFF, gcol=g4c, gkey='g4c', stage_cols=512)
    W2 = k.load_weight("W2", w2, FC, D, stage_cols=512)
    ht = [k.sb(f"ht{i}", [128, D]) for i in range(4)]
    xn = [k.sb(f"xn{i}", [128, D], BF16) for i in range(2)]
    xT = k.sb("xT", [128, KC, 512], BF16)
    AT = k.sb("AT", [128, FC, 512], BF16)
    sq = [k.sb(f"sq{i}", [128, 512]) for i in range(2)]
    junk = k.sb("junk", [128, D], BF16)
    ss = [k.sb(f"ss{i}", [128, 1]) for i in range(2)]
    ss2 = [k.sb(f"ss2{i}", [128, 2]) for i in range(2)]
    rstd = [k.sb(f"rstd{i}", [128, 1]) for i in range(2)]
    rstd2 = [k.sb(f"rstdb{i}", [128, 1]) for i in range(2)]
    ss4 = k.sb("ss4", [128, 4])
    rs4 = k.sb("rs4", [128, 4])
    psT = k.ps("psT", [128, D], BF16)
    psU = [k.ps(f"psU{i}", [128, 512]) for i in range(3)]
    psD = [k.ps(f"psD{i}", [128, 512]) for i in range(4)]
    nu = 0
    for blk in range(NB):
        for tt in range(4):
            i = blk * 4 + tt
            k.dma('sp', ht[tt][:], hin[i * 128:(i + 1) * 128, :], w=[f'ht{tt}'])
        for tt in range(4):
            k.act(junk[:], ht[tt][:], AF.Square, [f'ht{tt}'], ['junk', f'nss{tt}'], accum_out=ss4[:, tt:tt + 1])
        k.ts('dve', rs4[:], ss4[:], 1.0 / D, EPS, ALU.mult, ALU.add, [f'nss{t_}' for t_ in range(4)], ['rs4'])
        k.act(rs4[:], rs4[:], AF.Sqrt, ['rs4'], ['rs4'])
        k.recip(rs4[:], rs4[:], ['rs4'], ['rs4'])
        for tt in range(4):
            b = tt % 2
            k.ts('dve', xn[b][:], ht[tt][:], rs4[:, tt:tt + 1], None, ALU.mult, None, [f'ht{tt}', 'rs4'], [f'xn{b}'])
            for kc in range(KC):
                k.tr(psT[:, kc * 128:(kc + 1) * 128], xn[b][:, kc * 128:(kc + 1) * 128], k.identb[:], [f'xn{b}'], ['psT'])
            k.cp('act', xT[:, :, tt * 128:(tt + 1) * 128], psT[:].rearrange("p (k t) -> p k t", k=KC), ['psT'], ['xT'])
        for fc in range(FC):
            pu = nu % 3
            nu += 1
            for kc in range(KC):
                k.mm(psU[pu][:], W1[:, kc, fc * 128:(fc + 1) * 128], xT[:, kc, :], kc == 0, kc == KC - 1,
                     [f'W1{kc}', 'xT'], [f'psU{pu}'])
            sb_ = fc % 2
            k.act(sq[sb_][:], psU[pu][:], AF.Square, [f'psU{pu}'], [f'sq{sb_}'])
            k.stt(AT[:, fc, :], psU[pu][:], 0.0, sq[sb_][:], ALU.is_gt, ALU.mult, [f'psU{pu}', f'sq{sb_}'], ['AT'])
        for tt in range(4):
            i = blk * 4 + tt
            b = i % 2
            rows = slice(i * 128, (i + 1) * 128)
            for cg in range(2):
                pd = 2 * b + cg
                for fc in range(FC):
                    k.mm(psD[pd][:], AT[:, fc, tt * 128:(tt + 1) * 128], W2[:, fc, cg * 512:(cg + 1) * 512],
                         fc == 0, fc == FC - 1, ['AT', f'W2{fc}'], [f'psD{pd}'])
            post_norm_res(k, [psD[2 * b][:], psD[2 * b + 1][:]], [f'psD{2 * b}', f'psD{2 * b + 1}'], ht[tt], f'ht{tt}',
                          g5bc, 'g5bc', [sq[0][:], sq[1][:]], ['sq0', 'sq1'], ss2[b], rstd2[b][:], junk, f'pn{b}')
            k.dma('pool', hout[rows, :], ht[tt][:], r=[f'ht{tt}'], final=True)
    return k.finish()


def build_C2(NTOK, k=None):
    k = k or K()
    NB = NTOK // 512
    MEM = 256
    hin = k.din("hin", [NTOK, D])
    mem = k.din("mem", [MEM, D])
    wq = k.din("wq", [D, D])
    wk = k.din("wk", [D, D])
    wv = k.din("wv", [D, D])
    wo = k.din("wo", [D, D])
    g2 = k.din("g2", [D])
    g3 = k.din("g3", [D])
    g6 = k.din("g6", [D])
    ident_d = k.din("ident", [128, 128])
    hout = k.dout("hout", [NTOK, D])
    k.consts(ident_d)
    g2c = k.gain_cols("g2c", g2)
    g6c = k.gain_cols("g6c", g6)
    g3bc = k.bcast_row("g3bc", g3, D)
    Wk = k.load_weight("Wk", wk, KC, D, gcol=g6c, gkey='g6c', stage_cols=1024)
    Wv = k.load_weight("Wv", wv, KC, D, gcol=g6c, gkey='g6c', stage_cols=1024)
    Wq = k.load_weight("Wq", wq, KC, D, gcol=g2c, gkey='g2c', stage_cols=1024)
    Wo = k.load_weight("Wo", wo, KC, D, stage_cols=1024)
    ht = [k.sb(f"ht{i}", [128, D]) for i in range(2)]
    xn = [k.sb(f"xn{i}", [128, D], BF16) for i in range(2)]
    xT = [k.sb(f"xT{i}", [128, KC, 512], BF16) for i in range(2)]
    memT = k.sb("memT", [128, KC, MEM], BF16)
    KT = k.sb("KT", [128, KC, MEM], BF16)
    V = k.sb("V", [128, 2, D], BF16)
    QT = [k.sb(f"QT{i}", [128, KC, 512], BF16) for i in range(2)]
    Pm = [k.sb(f"Pm{i}", [128, 4, MEM], BF16) for i in range(3)]
    Pn = [k.sb(f"Pn{i}", [128, 4, MEM], BF16) for i in range(3)]
    PT = [k.sb(f"PT{i}", [128, 8, 128], BF16) for i in range(3)]
    OT = [k.sb(f"OT{i}", [128, KC, 128], BF16) for i in range(3)]
    tmp = [k.sb(f"tmp{i}", [128, 512]) for i in range(2)]
    junk = k.sb("junk", [128, D], BF16)
    ss = [k.sb(f"ss{i}", [128, 1]) for i in range(2)]
    ss2 = [k.sb(f"ss2{i}", [128, 2]) for i in range(2)]
    rstd = [k.sb(f"rstd{i}", [128, 1]) for i in range(2)]
    rstd2 = [k.sb(f"rstdb{i}", [128, 1]) for i in range(2)]
    mx = [k.sb(f"mx{i}", [128, 4]) for i in range(3)]
    sm = [k.sb(f"sm{i}", [128, 4]) for i in range(3)]
    psT = k.ps("psT", [128, D], BF16)
    psA = k.ps("psA", [128, 1024])
    psS = k.ps("psS", [128, 1024])
    psX = k.ps("psX", [128, 1024])
    for mt in range(2):
        k.dma('sp', ht[mt][:], mem[mt * 128:(mt + 1) * 128, :], w=[f'ht{mt}'])
        norm_T(k, ht[mt][:], f'ht{mt}', xn[mt][:], f'xn{mt}', memT[:, :, mt * 128:(mt + 1) * 128], 'memT', psT[:], 'psT',
               ss[mt][:], rstd[mt][:], junk[:], f'n{mt}')
    for cc in range(KC):
        pa = cc % 2
        for kc in range(KC):
            k.mm(psA[:, pa * 512:pa * 512 + MEM], Wk[:, kc, cc * 128:(cc + 1) * 128], memT[:, kc, :], kc == 0, kc == KC - 1,
                 [f'Wk{kc}', 'memT'], [f'psA{pa}'])
        k.cp('act' if cc % 2 else 'dve', KT[:, cc, :], psA[:, pa * 512:pa * 512 + MEM], [f'psA{pa}'], [f'KT{cc}'])
    for mt in range(2):
        for cg in range(2):
            for kc in range(KC):
                k.mm(psX[:, cg * 512:(cg + 1) * 512], memT[:, kc, mt * 128:(mt + 1) * 128], Wv[:, kc, cg * 512:(cg + 1) * 512],
                     kc == 0, kc == KC - 1, ['memT', f'Wv{kc}'], [f'psX{cg}'])
            k.cp('act' if cg else 'dve', V[:, mt, cg * 512:(cg + 1) * 512], psX[:, cg * 512:(cg + 1) * 512], [f'psX{cg}'], [f'V{mt}{cg}'])
    xt6 = [k.sb(f"xt6_{i}", [128, D]) for i in range(6)]
    ss6 = [k.sb(f"ss6_{i}", [128, 1]) for i in range(4)]
    rs6 = [k.sb(f"rs6_{i}", [128, 1]) for i in range(5)]
    xn3 = [k.sb(f"xn3_{i}", [128, D], BF16) for i in range(3)]
    psTx = k.ps("psTx", [128, D], BF16)

    def tile(i):
        blk, tt = divmod(i, 4)
        xb = blk % 2
        b = i % 3
        rows = slice(i * 128, (i + 1) * 128)
        tsl = slice(tt * 128, (tt + 1) * 128)
        def T(lst, nm):
            j = i % len(lst)
            return lst[j], f'{nm}{j}'
        xt_, kxt = T(xt6, 'xt6'); ss_, kss = T(ss6, 'ss6'); rs_, krs = T(rs6, 'rs6'); xn_, kxn = T(xn3, 'xn3')
        hb = i % 2
        k.dma('sp', xt_[:], hin[rows, :], w=[kxt])
        yield
        k.act(junk[:], xt_[:], AF.Square, [kxt], ['junk', kss], accum_out=ss_[:])
        yield
        k.ts('dve', rs_[:], ss_[:], 1.0 / D, EPS, ALU.mult, ALU.add, [kss], [krs])
        yield
        k.act(rs_[:], rs_[:], AF.Sqrt, [krs], [krs])
        yield
        k.recip(rs_[:], rs_[:], [krs], [krs])
        k.ts('dve', xn_[:], xt_[:], rs_[:], None, ALU.mult, None, [kxt, krs], [kxn])
        yield
        for kc in range(KC):
            k.tr(psTx[:, kc * 128:(kc + 1) * 128], xn_[:, kc * 128:(kc + 1) * 128], k.identb[:], [kxn], ['psTx'])
        yield
        k.cp('act', xT[xb][:, :, tsl], psTx[:].rearrange("p (k t) -> p k t", k=KC), ['psTx'], [f'xT{xb}'])
        yield
        if tt == 3:
            for cc in range(KC):
                pa = cc % 2
                for kc in range(KC):
                    k.mm(psA[:, pa * 512:(pa + 1) * 512], Wq[:, kc, cc * 128:(cc + 1) * 128], xT[xb][:, kc, :], kc == 0, kc == KC - 1,
                         [f'Wq{kc}', f'xT{xb}'], [f'psA{pa}'])
                k.cp('act' if cc % 2 else 'dve', QT[xb][:, cc, :], psA[:, pa * 512:(pa + 1) * 512], [f'psA{pa}'], [f'QT{xb}{cc}'])
        yield
        yield
        yield
        yield
        for h in range(4):
            sb_ = h // 2
            for j in range(2):
                cc = 2 * h + j
                k.mm(psS[:, h * MEM:(h + 1) * MEM], QT[xb][:, cc, tsl], KT[:, cc, :], j == 0, j == 1,
                     [f'QT{xb}{cc}', f'KT{cc}'], [f'psS{sb_}'])
        k.P.op('dve', lambda e, b=b: e.tensor_reduce(out=mx[b][:], in_=psS[:].rearrange("p (h m) -> p h m", h=4),
                                                    axis=AX.X, op=ALU.max),
               reads=['psS0', 'psS1'], writes=[f'mx{b}'])
        k.ts('dve', mx[b][:], mx[b][:], -1.0 / 16.0, None, ALU.mult, None, [f'mx{b}'], [f'mx{b}'])
        for h in range(4):
            k.act(Pm[b][:, h, :], psS[:, h * MEM:(h + 1) * MEM], AF.Exp, [f'psS{h // 2}', f'mx{b}'], [f'Pm{b}', f'sm{b}'],
                  scale=1.0 / 16.0, bias=mx[b][:, h:h + 1], accum_out=sm[b][:, h:h + 1])
        k.recip(sm[b][:], sm[b][:], [f'sm{b}'], [f'sm{b}'])
        k.tt('dve', Pn[b][:], Pm[b][:], sm[b][:].unsqueeze(2).broadcast_to([128, 4, MEM]), ALU.mult,
             [f'Pm{b}', f'sm{b}'], [f'Pn{b}'])
        yield
        for h in range(4):
            for mt in range(2):
                k.tr(psT[:, (h * 2 + mt) * 128:(h * 2 + mt + 1) * 128], Pn[b][:, h, mt * 128:(mt + 1) * 128], k.identb[:],
                     [f'Pn{b}'], ['psT'])
        k.cp('act', PT[b][:], psT[:].rearrange("p (k t) -> p k t", k=8), ['psT'], [f'PT{b}'])
        for cc in range(KC):
            h = cc // 2
            pa = cc // 4
            for mt in range(2):
                k.mm(psA[:, cc * 128:(cc + 1) * 128], V[:, mt, cc * 128:(cc + 1) * 128], PT[b][:, h * 2 + mt, :],
                     mt == 0, mt == 1, [f'V{mt}{cc // 4}', f'PT{b}'], [f'psA{pa}'])
        k.cp('dve', OT[b][:, 0:4, :], psA[:, 0:512].rearrange("p (k t) -> p k t", k=4), ['psA0'], [f'OT{b}_0'])
        k.cp('act', OT[b][:, 4:8, :], psA[:, 512:1024].rearrange("p (k t) -> p k t", k=4), ['psA1'], [f'OT{b}_1'])
        k.dma('sp', ht[hb][:], hin[rows, :], w=[f'ht{hb}'])
        yield
        for cg in range(2):
            for cc in range(KC):
                k.mm(psX[:, cg * 512:(cg + 1) * 512], OT[b][:, cc, :], Wo[:, cc, cg * 512:(cg + 1) * 512],
                     cc == 0, cc == KC - 1, [f'OT{b}_{cc // 4}', f'Wo{cc}'], [f'psX{cg}'])
        post_norm_res(k, [psX[:, 0:512], psX[:, 512:1024]], ['psX0', 'psX1'], ht[hb], f'ht{hb}',
                      g3bc, 'g3bc', [tmp[0][:], tmp[1][:]], ['tmp0', 'tmp1'], ss2[b % 2], rstd2[b % 2][:], junk, f'pn{b % 2}')
        k.dma('pool', hout[rows, :], ht[hb][:], r=[f'ht{hb}'], final=True)

    pipeline(tile, NTOK // 128)
    return k.finish()


def build_A2(NTOK, NC, fm, NF, k=None):
    k = k or K()
    NB = NTOK // 512
    x = k.din("x", [NTOK, D])
    gain = k.din("gain", [D])
    W = k.din("W", [D, NC])
    ident_d = k.din("ident", [128, 128])
    out = k.dout("out", [NTOK, NC])
    outT = k.dout("outT", [NF, NTOK])
    k.consts(ident_d)
    gc = k.gain_cols("gc", gain)
    Wb = k.load_weight("Wb", W, KC, NC, gcol=gc, gkey='gc', stage_cols=1408)
    cgs = [(c0, min(512, NC - c0)) for c0 in range(0, NC, 512)]
    def ring(nm, shape, n, dt=F32):
        return [k.sb(f"{nm}{j}", shape, dt) for j in range(n)]
    xt = ring("xt", [128, D], 6)
    xn = ring("xn", [128, D], 3, BF16)
    xT = [k.sb(f"xT{i}", [128, KC, 512], BF16) for i in range(2)]
    ot = [k.sb(f"ot{i}", [128, NC]) for i in range(2)]
    ft = [k.sb(f"ft{i}", [128, 512]) for i in range(2)]
    junk = k.sb("junk", [128, D], BF16)
    ss = ring("ss", [128, 1], 4)
    rstd = ring("rstd", [128, 1], 5)
    psT = k.ps("psT", [128, D], BF16)
    psO = [k.ps(f"psO{i}", [128, 512]) for i in range(4)]
    psF = [k.ps(f"psF{i}", [128, 512]) for i in range(2)]
    cnt = {'no': 0, 'nf': 0}

    def tile(i):
        blk, tt = divmod(i, 4)
        xb = blk % 2
        def T(lst, nm):
            j = i % len(lst)
            return lst[j], f'{nm}{j}'
        xt_, kxt = T(xt, 'xt'); xn_, kxn = T(xn, 'xn'); ss_, kss = T(ss, 'ss'); rs_, krs = T(rstd, 'rstd')
        k.dma('sp', xt_[:], x[i * 128:(i + 1) * 128, :], w=[kxt])
        yield
        k.act(junk[:], xt_[:], AF.Square, [kxt], ['junk', kss], accum_out=ss_[:])
        yield
        k.ts('dve', rs_[:], ss_[:], 1.0 / D, EPS, ALU.mult, ALU.add, [kss], [krs])
        yield
        k.act(rs_[:], rs_[:], AF.Sqrt, [krs], [krs])
        yield
        k.recip(rs_[:], rs_[:], [krs], [krs])
        k.ts('dve', xn_[:], xt_[:], rs_[:], None, ALU.mult, None, [kxt, krs], [kxn])
        yield
        for kc in range(KC):
            k.tr(psT[:, kc * 128:(kc + 1) * 128], xn_[:, kc * 128:(kc + 1) * 128], k.identb[:], [kxn], ['psT'])
        yield
        k.cp('act', xT[xb][:, :, tt * 128:(tt + 1) * 128], psT[:].rearrange("p (k t) -> p k t", k=KC), ['psT'], [f'xT{xb}'])
        yield
        if tt != 3:
            return
        for t2 in range(4):
            i2 = blk * 4 + t2
            b = i2 % 2
            for ci, (c0, cw) in enumerate(cgs):
                pb = cnt['no'] % 4
                cnt['no'] += 1
                for kc in range(KC):
                    k.mm(psO[pb][:, 0:cw], xT[xb][:, kc, t2 * 128:(t2 + 1) * 128], Wb[:, kc, c0:c0 + cw], kc == 0, kc == KC - 1,
                         [f'xT{xb}', f'Wb{kc}'], [f'psO{pb}'])
                k.cp('dve' if pb % 2 == 0 else 'act', ot[b][:, c0:c0 + cw], psO[pb][:, 0:cw], [f'psO{pb}'], [f'ot{b}_{pb % 2}'])
            k.dma('pool', out[i2 * 128:(i2 + 1) * 128, :], ot[b][:], r=[f'ot{b}_0', f'ot{b}_1'], final=True)
        for (c0, cw, r0) in fm:
            pf = cnt['nf'] % 2
            cnt['nf'] += 1
            for kc in range(KC):
                k.mm(psF[pf][0:cw, :], Wb[:, kc, c0:c0 + cw], xT[xb][:, kc, :], kc == 0, kc == KC - 1,
                     [f'Wb{kc}', f'xT{xb}'], [f'psF{pf}'])
            k.cp('dve' if pf == 0 else 'act', ft[pf][0:cw, :], psF[pf][0:cw, :], [f'psF{pf}'], [f'ft{pf}'])
            k.dma('pool', outT[r0:r0 + cw, blk * 512:(blk + 1) * 512], ft[pf][0:cw, :], r=[f'ft{pf}'], final=True)

    pipeline(tile, NTOK // 128)
    return k.finish()


def gen_GLA(L, k):
    NT = L // 128
    qT = k.din("qT", [128, L])
    kT = k.din("kT", [128, L])
    ktok = k.din("ktok", [L, 128])
    v = k.din("v", [L, 256])
    gate = k.din("gate", [L, 256])
    dlrT = k.din("dlrT", [16, L])
    w2 = k.din("w2", [16, 128])
    bdec = k.din("bdec", [1, 128])
    gn = k.din("gn", [256])
    triu_d = k.din("triu", [128, 128])
    trigt_d = k.din("trigt", [128, 128])
    oa = k.dout("oa", [L, 256])

    triu = k.sb("triu_s", [128, 128])
    trigt = k.sb("trigt_s", [128, 128])
    k.dma('sp', triu[:], triu_d, w=['triu'])
    k.dma('sp', trigt[:], trigt_d, w=['trigt'])
    w2s = k.sb("w2s", [16, 128])
    k.dma('sp', w2s[:], w2, w=['w2s'])
    bds = k.sb("bds", [1, 128])
    k.dma('sp', bds[:], bdec, w=['bds'])
    ones1 = k.sb("ones1", [1, 128])
    k.memset('dve', ones1[:], 1.0, ['ones1'])
    gnbc = k.bcast_row("gnbc", gn, 256)
    S = k.sb("S", [128, 128], mybir.dt.float32r)
    zS = k.sb("zS", [128, 128])
    k.memset('dve', zS[:], 0.0, ['zS'])
    k.cp('dve', S[:], zS[:], ['zS'], ['S'])
    rm = k.sb("rm", [128, 2])
    k.memset('dve', rm[:], 0.0, ['rm'])
    k.memset('dve', rm[0:64, 0:1], 0.125, ['rm'])
    k.memset('dve', rm[64:128, 1:2], 0.125, ['rm'])

    def ring(nm, shape, n, dt=F32):
        return [k.sb(f"{nm}{j}", shape, dt) for j in range(n)]
    FR_ = mybir.dt.float32r
    triur = k.sb("triur", [128, 128], FR_)
    trigtr = k.sb("trigtr", [128, 128], FR_)
    k.cp('dve', triur[:], triu[:], ['triu'], ['triur'])
    k.cp('dve', trigtr[:], trigt[:], ['trigt'], ['trigtr'])
    vr = ring("vr", [128, 256], 10, FR_)
    qTt, kTt, kt, gt = ring("qTt", [128, 128], 8), ring("kTt", [128, 128], 8), ring("kt", [128, 128], 8), ring("gt", [128, 256], 8)
    vt = ring("vt", [128, 256], 11)
    dt_ = ring("dt", [16, 128], 3)
    la = ring("la", [128, 128], 4, mybir.dt.float32r)
    sg = ring("sg", [128, 256], 16)
    EqT, EkT, Eks = ring("EqT", [128, 128], 7), ring("EkT", [128, 128], 3), ring("Eks", [128, 128], 3)
    qin, kin, kst = ring("qin", [128, 2, 128], 5, mybir.dt.float32r), ring("kin", [128, 128], 3, mybir.dt.float32r), ring("kst", [128, 128], 5, mybir.dt.float32r)
    sc0, sc1 = ring("sc0_", [128, 128], 3, mybir.dt.float32r), ring("sc1_", [128, 128], 3, mybir.dt.float32r)
    osr = ring("osr", [128, 256], 6)
    osb = ring("osb", [128, 256], 3)
    ss, rs = ring("ss", [128, 2], 4), ring("rs", [128, 2], 5)
    ot = ring("ot", [128, 256], 3)
    junk = k.sb("junk", [128, 128])
    psZ = [k.ps(f"psZ{j}", [128, 512]) for j in range(2)]
    psA = [k.ps(f"psA{j}", [128, 512]) for j in range(2)]
    psB = [k.ps(f"psB{j}", [128, 512]) for j in range(2)]
    psC = [k.ps(f"psC{j}", [128, 512]) for j in range(2)]

    def tile(i):
        rows = slice(i * 128, (i + 1) * 128)
        R = lambda lst: (lst[i % len(lst)], f'{lst[0].name if hasattr(lst[0], "name") else id(lst)}_{i % len(lst)}')
        def T(lst, nm):
            j = i % len(lst)
            return lst[j], f'{nm}{j}'
        q_, kq = T(qTt, 'qTt'); kT_, kkT = T(kTt, 'kTt'); kt_, kkt = T(kt, 'kt'); v_, kv = T(vt, 'vt'); g_, kg = T(gt, 'gt')
        d_, kd = T(dt_, 'dt'); la_, kla = T(la, 'la'); sg_, ksg = T(sg, 'sg')
        Eq, kEq = T(EqT, 'EqT'); Ek, kEk = T(EkT, 'EkT'); Es, kEs = T(Eks, 'Eks')
        qi, kqi = T(qin, 'qin'); ki, kki = T(kin, 'kin'); ks, kks = T(kst, 'kst')
        scs = [T(sc0, 'sc0_'), T(sc1, 'sc1_')]
        orw, korw = T(osr, 'osr'); ob_, kob = T(osb, 'osb'); ss_, kss = T(ss, 'ss'); rs_, krs = T(rs, 'rs'); ot_, kot = T(ot, 'ot')
        pz, kpz = psZ[i % 2], f'psZ{i % 2}'
        pa, kpa = psA[i % 2], f'psA{i % 2}'
        pb, kpb = psB[i % 2], f'psB{i % 2}'
        pc, kpc = psC[i % 2], f'psC{i % 2}'
        k.dma('sp', q_[:], qT[:, rows], w=[kq])
        k.dma('sp', kT_[:], kT[:, rows], w=[kkT])
        k.dma('sp', kt_[:], ktok[rows, :], w=[kkt])
        k.dma('sp', v_[:], v[rows, :], w=[kv])
        k.dma('sp', g_[:], gate[rows, :], w=[kg])
        k.dma('sp', d_[:], dlrT[:, rows], w=[kd])
        yield
        k.mm(pz[:, 0:128], d_[:], w2s[:], True, False, [kd, 'w2s'], [kpz])
        k.mm(pz[:, 0:128], ones1[:], bds[:], False, True, ['ones1', 'bds'], [kpz])
        yield
        k.act(la_[:], pz[:, 0:128], AF.Exp, [kpz], [kla], scale=-1.0)
        k.act(la_[:], la_[:].bitcast(F32), AF.Ln, [kla], [kla], bias=1.0)
        k.act(sg_[:], g_[:], AF.Exp, [kg], [ksg], scale=-1.0)
        vr_, kvr = T(vr, 'vr')
        k.cp('act', vr_[:], v_[:], [kv], [kvr])
        yield
        k.ts('dve', la_[:], la_[:].bitcast(F32), -1.0 / 16.0, None, ALU.mult, None, [kla], [kla])
        k.ts('dve', sg_[:], sg_[:], 1.0, None, ALU.add, None, [ksg], [ksg])
        k.recip(sg_[:], sg_[:], [ksg], [ksg])
        yield
        k.mm(pa[:, 0:128], la_[:], triur[:], True, True, [kla, 'triur'], [kpa])
        k.mm(pa[:, 128:256], trigtr[:], la_[:], True, True, [kla, 'trigtr'], [kpa])
        yield
        k.act(Eq[:], pa[:, 0:128], AF.Exp, [kpa], [kEq])
        k.act(Ek[:], pa[:, 0:128], AF.Exp, [kpa], [kEk], scale=-1.0)
        k.act(Es[:], pa[:, 128:256], AF.Exp, [kpa], [kEs])
        yield
        for h in range(2):
            k.stt(qi[:, h, :], q_[:], rm[:, h:h + 1], Eq[:], ALU.mult, ALU.mult, [kq, kEq, 'rm'], [kqi])
        k.tt('pool', ki[:], kT_[:], Ek[:], ALU.mult, [kkT, kEk], [kki])
        k.tt('pool', ks[:], kt_[:], Es[:], ALU.mult, [kkt, kEs], [kks])
        k.tt('pool', sg_[:], sg_[:], g_[:], ALU.mult, [ksg, kg], [ksg])
        yield
        for h in range(2):
            hp = slice(h * 64, (h + 1) * 64)
            k.mm(pb[:, h * 128:(h + 1) * 128], ki[:], qi[:, h, :], True, True, [kki, kqi], [kpb])
        yield
        for h in range(2):
            k.tt('dve', scs[h][0][:], pb[:, h * 128:(h + 1) * 128], triu[:], ALU.mult, [kpb, 'triu'], [scs[h][1]])
        yield
        for h in range(2):
            hp = slice(h * 64, (h + 1) * 64)
            k.mm(pc[:, h * 128:(h + 1) * 128], scs[h][0][:], vr_[:, h * 128:(h + 1) * 128], True, False, [scs[h][1], kvr], [kpc])
            k.mm(pc[:, h * 128:(h + 1) * 128], qi[:, h, :], S[:], False, True, [kqi, 'S'], [kpc])
        k.mm(pc[:, 256:512], ks[:], vr_[:], True, True, [kks, kvr], [kpc])
        yield
        for h in range(2):
            hp = slice(h * 64, (h + 1) * 64)
            k.stt(S[hp, :], S[hp, :].bitcast(F32), Eq[hp, 127:128], pc[hp, 256 + h * 128:256 + (h + 1) * 128], ALU.mult, ALU.add,
                  ['S', kEq, kpc], ['S'])
        k.cp('act', orw[:], pc[:, 0:256], [kpc], [korw])
        yield
        for h in range(2):
            k.act(junk[:], orw[:, h * 128:(h + 1) * 128], AF.Square, [korw], ['junk', kss], accum_out=ss_[:, h:h + 1])
        yield
        k.ts('dve', rs_[:], ss_[:], 1.0 / 128.0, EPS, ALU.mult, ALU.add, [kss], [krs])
        yield
        k.act(rs_[:], rs_[:], AF.Ln, [krs], [krs])
        k.act(rs_[:], rs_[:], AF.Exp, [krs], [krs], scale=-0.5)
        yield
        for h in range(2):
            hs = slice(h * 128, (h + 1) * 128)
            k.stt(ob_[:, hs], orw[:, hs], rs_[:, h:h + 1], gnbc[:, hs], ALU.mult, ALU.mult, [korw, krs, 'gnbc'], [kob])
        yield
        k.tt('pool', ot_[:], ob_[:], sg_[:], ALU.mult, [kob, ksg], [kot])
        k.dma('pool', oa[rows, :], ot_[:], r=[kot], final=True)

    yield from pipeline_gen(tile, NT)


def build_GLA(L, k=None):
    k = k or K()
    for _ in gen_GLA(L, k):
        pass
    return k.finish()


TWO_PI = 2.0 * math.pi
C1 = 6.28125
C2 = TWO_PI - 6.28125
PI_LO = 3.1415925


def range_sincos(k, x, xkey, shape, s_out, c_out, skey, ckey, pfx):
    if not hasattr(k, 'rr_cache'):
        k.rr_cache = {}
    if pfx not in k.rr_cache:
        k.rr_cache[pfx] = (k.sb(pfx + "kf", shape), k.sb(pfx + "ki", shape, I32), k.sb(pfx + "r", shape), k.sb(pfx + "m", shape))
    kf, ki, r, m = k.rr_cache[pfx]
    a = lambda t: t[:]
    K1, K2, K3, K4 = pfx + 'kf', pfx + 'ki', pfx + 'r', pfx + 'm'
    k.ts('dve', a(kf), x, 1.0 / TWO_PI, None, ALU.mult, None, [xkey], [K1])
    k.cp('dve', a(ki), a(kf), [K1], [K2])
    k.cp('dve', a(kf), a(ki), [K2], [K1])
    k.stt(a(r), a(kf), -C1, x, ALU.mult, ALU.add, [K1, xkey], [K3])
    k.stt(a(r), a(kf), -C2, a(r), ALU.mult, ALU.add, [K1, K3], [K3])
    k.ts('dve', a(m), a(r), math.pi, -TWO_PI, ALU.is_gt, ALU.mult, [K3], [K4])
    k.tt('dve', a(r), a(r), a(m), ALU.add, [K3, K4], [K3])
    k.ts('dve', a(m), a(r), -math.pi, TWO_PI, ALU.is_lt, ALU.mult, [K3], [K4])
    k.tt('dve', a(r), a(r), a(m), ALU.add, [K3, K4], [K3])
    k.ts('dve', a(kf), a(r), PI_LO, -PI_LO, ALU.min, ALU.max, [K3], [K1])
    k.act(s_out, a(kf), AF.Sin, [K1], [skey])
    k.ts('dve', a(r), a(r), math.pi / 2, None, ALU.add, None, [K3], [K3])
    k.ts('dve', a(m), a(r), math.pi, -TWO_PI, ALU.is_gt, ALU.mult, [K3], [K4])
    k.tt('dve', a(r), a(r), a(m), ALU.add, [K3, K4], [K3])
    k.ts('dve', a(kf), a(r), PI_LO, -PI_LO, ALU.min, ALU.max, [K3], [K1])
    k.act(c_out, a(kf), AF.Sin, [K1], [ckey])


def gen_S5(L, k):
    NT = L // 128
    NS = 1024
    uT = k.din("uT", [256, L])
    u = k.din("u", [L, 256])
    lam_re = k.din("lam_re", [NS])
    lam_im = k.din("lam_im", [NS])
    lstep = k.din("lstep", [NS])
    Bre = k.din("Bre", [2, 128, 512])
    Bim = k.din("Bim", [2, 128, 512])
    Cre = k.din("Cre", [8, 128, 32])
    Cim = k.din("Cim", [8, 128, 32])
    dsk = k.din("dsk", [256])
    triu_d = k.din("triu", [128, 128])
    iop_d = k.din("iota_p", [128, 1])
    iof_d = k.din("iota_f", [128, 128])
    y = k.dout("y", [L, 256])

    k.push_scope([("triu_s", [128, 128], F32), ("dbc", [128, 256], F32), ("BBr", [128, 2, 512], mybir.dt.float32r), ("BBi", [128, 2, 512], mybir.dt.float32r),
                  ("Pr", [128, NS], F32), ("Pi", [128, NS], F32), ("Qr", [128, 8, 128], F32), ("Qi", [128, 8, 128], F32),
                  ("L128r", [128, 8], F32), ("L128i", [128, 8], F32), ("Cr", [128, 8, 32], F32), ("nCi", [128, 8, 32], F32),
                  ("car_r", [128, 8], F32), ("car_i", [128, 8], F32), ("ntriu", [128, 128], mybir.dt.float32r), ("nCr", [128, 8, 32], mybir.dt.float32r), ("triur", [128, 128], mybir.dt.float32r), ("Crr", [128, 8, 32], mybir.dt.float32r), ("nCir", [128, 8, 32], mybir.dt.float32r)])
    triu = k.sb("triu_s", [128, 128])
    k.dma('sp', triu[:], triu_d, w=['triu'])
    iop = k.sb("iop", [128, 1])
    k.dma('sp', iop[:], iop_d, w=['iop'])
    negp = k.sb("negp", [128, 1])
    k.ts('dve', negp[:], iop[:], -1.0, None, ALU.mult, None, ['iop'], ['negp'])
    iof = k.sb("iof", [128, 128])
    k.dma('sp', iof[:], iof_d, w=['iof'])
    dbc = k.bcast_row("dbc", dsk, 256)
    R = [128, NS]
    lr = k.bcast_row("lr", lam_re, NS)
    li = k.bcast_row("li", lam_im, NS)
    dl = k.bcast_row("dl", lstep, NS)
    k.ts('dve', lr[:], lr[:], -1e-4, None, ALU.min, None, ['lr'], ['lr'])
    k.act(dl[:], dl[:], AF.Exp, ['dl'], ['dl'])
    a_ = k.sb("a_", R)
    th = k.sb("th", R)
    k.tt('dve', a_[:], lr[:], dl[:], ALU.mult, ['lr', 'dl'], ['a_'])
    k.tt('dve', th[:], li[:], dl[:], ALU.mult, ['li', 'dl'], ['th'])
    sn = k.sb("sn", R)
    cs = k.sb("cs", R)
    range_sincos(k, th[:], 'th', R, sn[:], cs[:], 'sn', 'cs', 'rr_')
    ea = k.sb("ea", R)
    k.act(ea[:], a_[:], AF.Exp, ['a_'], ['ea'])
    nr = k.sb("nr", R)
    ni = k.sb("ni", R)
    k.tt('dve', nr[:], ea[:], cs[:], ALU.mult, ['ea', 'cs'], ['nr'])
    k.ts('dve', nr[:], nr[:], -1.0, None, ALU.add, None, ['nr'], ['nr'])
    k.tt('dve', ni[:], ea[:], sn[:], ALU.mult, ['ea', 'sn'], ['ni'])
    den = k.sb("den", R)
    t0 = k.sb("t0", R)
    k.tt('dve', den[:], lr[:], lr[:], ALU.mult, ['lr'], ['den'])
    k.tt('dve', t0[:], li[:], li[:], ALU.mult, ['li'], ['t0'])
    k.tt('dve', den[:], den[:], t0[:], ALU.add, ['den', 't0'], ['den'])
    k.recip(den[:], den[:], ['den'], ['den'])
    gr = k.sb("gr", R)
    gi = k.sb("gi", R)
    k.tt('dve', gr[:], nr[:], lr[:], ALU.mult, ['nr', 'lr'], ['gr'])
    k.tt('dve', t0[:], ni[:], li[:], ALU.mult, ['ni', 'li'], ['t0'])
    k.tt('dve', gr[:], gr[:], t0[:], ALU.add, ['gr', 't0'], ['gr'])
    k.tt('dve', gr[:], gr[:], den[:], ALU.mult, ['gr', 'den'], ['gr'])
    k.tt('dve', gi[:], ni[:], lr[:], ALU.mult, ['ni', 'lr'], ['gi'])
    k.tt('dve', t0[:], nr[:], li[:], ALU.mult, ['nr', 'li'], ['t0'])
    k.tt('dve', gi[:], gi[:], t0[:], ALU.subtract, ['gi', 't0'], ['gi'])
    k.tt('dve', gi[:], gi[:], den[:], ALU.mult, ['gi', 'den'], ['gi'])
    Br = k.sb("Br", [128, 2, 512])
    Bi = k.sb("Bi", [128, 2, 512])
    BBr = k.sb("BBr", [128, 2, 512])
    BBi = k.sb("BBi", [128, 2, 512])
    for hc in range(2):
        k.dma('sp', Br[:, hc, :], Bre[hc], w=[f'Br{hc}'])
        k.dma('sp', Bi[:, hc, :], Bim[hc], w=[f'Bi{hc}'])
    grv = gr[:].rearrange("p (h n) -> p h n", h=2)
    giv = gi[:].rearrange("p (h n) -> p h n", h=2)
    t0v = t0[:].rearrange("p (h n) -> p h n", h=2)
    BK = ['Br0', 'Br1', 'Bi0', 'Bi1']
    k.tt('dve', BBr[:], grv, Br[:], ALU.mult, ['gr'] + BK, ['BBr'])
    k.tt('dve', t0v, giv, Bi[:], ALU.mult, ['gi'] + BK, ['t0'])
    k.tt('dve', BBr[:], BBr[:].bitcast(F32), t0v, ALU.subtract, ['BBr', 't0'], ['BBr'])
    k.tt('dve', BBi[:], grv, Bi[:], ALU.mult, ['gr'] + BK, ['BBi'])
    k.tt('dve', t0v, giv, Br[:], ALU.mult, ['gi'] + BK, ['t0'])
    k.tt('dve', BBi[:], BBi[:].bitcast(F32), t0v, ALU.add, ['BBi', 't0'], ['BBi'])
    ang = k.sb("ang", R)
    k.ts('dve', ang[:], th[:], iop[:, 0:1], None, ALU.mult, None, ['th', 'iop'], ['ang'])
    Pr = k.sb("Pr", R)
    Pi = k.sb("Pi", R)
    range_sincos(k, ang[:], 'ang', R, sn[:], cs[:], 'sn', 'cs', 'rr_')
    k.act(ea[:], a_[:], AF.Exp, ['a_', 'negp'], ['ea'], scale=negp[:, 0:1])
    k.tt('dve', Pr[:], ea[:], cs[:], ALU.mult, ['ea', 'cs'], ['Pr'])
    k.stt(Pi[:], ea[:], -1.0, sn[:], ALU.mult, ALU.mult, ['ea', 'sn'], ['Pi'])
    Cs = [128, 8]
    lrc = k.sb("lrc", Cs)
    lic = k.sb("lic", Cs)
    dlc = k.sb("dlc", Cs)
    cv = lambda d: d.rearrange("(blk p) -> p blk", p=128)
    k.dma('sp', lrc[:], cv(lam_re), w=['lrc'], allow_slow_non_contiguous=True)
    k.dma('sp', lic[:], cv(lam_im), w=['lic'], allow_slow_non_contiguous=True)
    k.dma('sp', dlc[:], cv(lstep), w=['dlc'], allow_slow_non_contiguous=True)
    k.ts('dve', lrc[:], lrc[:], -1e-4, None, ALU.min, None, ['lrc'], ['lrc'])
    k.act(dlc[:], dlc[:], AF.Exp, ['dlc'], ['dlc'])
    ac = k.sb("ac", Cs)
    thc = k.sb("thc", Cs)
    k.tt('dve', ac[:], lrc[:], dlc[:], ALU.mult, ['lrc', 'dlc'], ['ac'])
    k.tt('dve', thc[:], lic[:], dlc[:], ALU.mult, ['lic', 'dlc'], ['thc'])
    Qr = k.sb("Qr", [128, 8, 128])
    Qi = k.sb("Qi", [128, 8, 128])
    angv = ang[:].rearrange("p (b t) -> p b t", b=8)
    eav = ea[:].rearrange("p (b t) -> p b t", b=8)
    for blk in range(8):
        k.ts('dve', angv[:, blk, :], iof[:], thc[:, blk:blk + 1], None, ALU.mult, None, ['iof', 'thc'], ['ang'])
    range_sincos(k, ang[:], 'ang', R, sn[:], cs[:], 'sn', 'cs', 'rr_')
    for blk in range(8):
        k.act(eav[:, blk, :], iof[:], AF.Exp, ['iof', 'ac'], ['ea'], scale=ac[:, blk:blk + 1])
    k.tt('dve', Qr[:].rearrange("p b t -> p (b t)"), ea[:], cs[:], ALU.mult, ['ea', 'cs'], ['Qr'])
    k.tt('dve', Qi[:].rearrange("p b t -> p (b t)"), ea[:], sn[:], ALU.mult, ['ea', 'sn'], ['Qi'])
    a128 = k.sb("a128", Cs)
    s128 = k.sb("s128", Cs)
    c128 = k.sb("c128", Cs)
    L128r = k.sb("L128r", Cs)
    L128i = k.sb("L128i", Cs)
    k.ts('dve', a128[:], thc[:], 128.0, None, ALU.mult, None, ['thc'], ['a128'])
    range_sincos(k, a128[:], 'a128', Cs, s128[:], c128[:], 's128', 'c128', 'rc_')
    k.act(a128[:], ac[:], AF.Exp, ['ac', 's128', 'c128'], ['a128'], scale=128.0)
    k.tt('dve', L128r[:], a128[:], c128[:], ALU.mult, ['a128', 'c128'], ['L128r'])
    k.tt('dve', L128i[:], a128[:], s128[:], ALU.mult, ['a128', 's128'], ['L128i'])
    Cr = k.sb("Cr", [128, 8, 32])
    nCi = k.sb("nCi", [128, 8, 32])
    k.dma('sp', Cr[:], Cre.rearrange("b p c -> p b c"), w=['Cr'])
    k.dma('sp', nCi[:], Cim.rearrange("b p c -> p b c"), w=['nCi'])
    k.ts('dve', nCi[:], nCi[:], -1.0, None, ALU.mult, None, ['nCi'], ['nCi'])
    car_r = k.sb("car_r", Cs)
    car_i = k.sb("car_i", Cs)
    k.memset('dve', car_r[:], 0.0, ['car_r0', 'car_r1'])
    k.memset('dve', car_i[:], 0.0, ['car_i0', 'car_i1'])
    ntriu = k.sb("ntriu", [128, 128])
    k.ts('dve', ntriu[:], triu[:], -1.0, None, ALU.mult, None, ['triu'], ['ntriu'])
    nCr = k.sb("nCr", [128, 8, 32])
    k.ts('dve', nCr[:], Cr[:], -1.0, None, ALU.mult, None, ['Cr'], ['nCr'])
    triur = k.sb("triur", [128, 128])
    k.cp('dve', triur[:], triu[:], ['triu'], ['triur'])
    Crr = k.sb("Crr", [128, 8, 32])
    k.cp('dve', Crr[:], Cr[:], ['Cr'], ['Crr'])
    nCir = k.sb("nCir", [128, 8, 32])
    k.cp('dve', nCir[:], nCi[:], ['nCi'], ['nCir'])
    k.pop_scope()
    if hasattr(k, 'rr_cache'):
        del k.rr_cache
    def ring(nm, shape, n, dt=F32):
        return [k.sb(f"{nm}{j}", shape, dt) for j in range(n)]
    FR_ = mybir.dt.float32r
    uTt = ring("uTt", [128, 128], 3)
    uTr = ring("uTr", [128, 128], 3, FR_)
    ut = ring("ut", [128, 128], 5)
    yo = ring("yo", [128, 128], 9)
    m1, m2, m3, m4 = ring("m1_", [128, 512], 3, FR_), ring("m2_", [128, 512], 3, FR_), ring("m3_", [128, 512], 3, FR_), ring("m4_", [128, 512], 3, FR_)
    Xtr, Xti = ring("Xtr", [128, 512], 3), ring("Xti", [128, 512], 3)
    Gr, Gi = ring("Gr", [128, 4, 128], 3), ring("Gi", [128, 4, 128], 3)
    n1, n2, n3, n4 = ring("n1_", [128, 512], 3, FR_), ring("n2_", [128, 512], 3, FR_), ring("n3_", [128, 512], 3, FR_), ring("n4_", [128, 512], 3, FR_)
    Hr, Hi = ring("Hr", [128, 4, 128], 3), ring("Hi", [128, 4, 128], 3)
    cc1 = [k.sb(f"cc1_{h}", [128, 4]) for h in range(2)]
    cc2 = [k.sb(f"cc2_{h}", [128, 4]) for h in range(2)]
    psXr = k.ps("psXr", [128, 512])
    psXi = k.ps("psXi", [128, 512])
    psGr = k.ps("psGr", [128, 512])
    psGi = k.ps("psGi", [128, 512])
    psY = k.ps("psY", [128, 512])
    fl = lambda t: t[:].rearrange("p b t -> p (b t)")

    def item(j):
        i, hc = divmod(j, 2)
        rows = slice(i * 128, (i + 1) * 128)
        cs_ = slice(hc * 512, (hc + 1) * 512)
        bs = slice(hc * 4, (hc + 1) * 4)
        def T(lst, nm):
            q = j % len(lst)
            return lst[q], f'{nm}{q}'
        uT_, kuT = T(uTt, 'uTt'); uR_, kuR = T(uTr, 'uTr'); ut_, kut = T(ut, 'ut'); yo_, kyo = T(yo, 'yo')
        m1_, km1 = T(m1, 'm1'); m2_, km2 = T(m2, 'm2'); m3_, km3 = T(m3, 'm3'); m4_, km4 = T(m4, 'm4')
        Xr_, kXr = T(Xtr, 'Xtr'); Xi_, kXi = T(Xti, 'Xti'); Gr_, kGr = T(Gr, 'Gr'); Gi_, kGi = T(Gi, 'Gi')
        n1_, kn1 = T(n1, 'n1'); n2_, kn2 = T(n2, 'n2'); n3_, kn3 = T(n3, 'n3'); n4_, kn4 = T(n4, 'n4')
        Hr_, kHr = T(Hr, 'Hr'); Hi_, kHi = T(Hi, 'Hi')
        k.dma('sp', uT_[:], uT[hc * 128:(hc + 1) * 128, rows], w=[kuT])
        k.dma('sp', ut_[:], u[rows, hc * 128:(hc + 1) * 128], w=[kut])
        yield
        k.cp('act', uR_[:], uT_[:], [kuT], [kuR])
        yield
        k.mm(psXr[:], uR_[:], BBr[:, hc, :], True, True, [kuR, 'BBr'], ['psXr'])
        k.mm(psXi[:], uR_[:], BBi[:, hc, :], True, True, [kuR, 'BBi'], ['psXi'])
        yield
        k.tt('dve', m1_[:], psXr[:], Pr[:, cs_], ALU.mult, ['psXr', 'Pr'], [km1])
        k.tt('dve', m3_[:], psXr[:], Pi[:, cs_], ALU.mult, ['psXr', 'Pi'], [km3])
        k.tt('dve', m2_[:], psXi[:], Pi[:, cs_], ALU.mult, ['psXi', 'Pi'], [km2])
        k.tt('dve', m4_[:], psXi[:], Pr[:, cs_], ALU.mult, ['psXi', 'Pr'], [km4])
        yield
        k.tt('pool', yo_[:], ut_[:], dbc[:, hc * 128:(hc + 1) * 128], ALU.mult, [kut, 'dbc'], [kyo])
        yield
        for nb in range(4):
            ns = slice(nb * 128, (nb + 1) * 128)
            k.mm(psGr[:, ns], m1_[:, ns], triur[:], True, False, [km1, 'triur'], ['psGr'])
            k.mm(psGr[:, ns], m2_[:, ns], ntriu[:], False, True, [km2, 'ntriu'], ['psGr'])
            k.mm(psGi[:, ns], m3_[:, ns], triur[:], True, False, [km3, 'triur'], ['psGi'])
            k.mm(psGi[:, ns], m4_[:, ns], triur[:], False, True, [km4, 'triur'], ['psGi'])
        yield
        k.tt('dve', Gr_[:], psGr[:].rearrange("p (b t) -> p b t", b=4),
             car_r[:, bs].unsqueeze(2).broadcast_to([128, 4, 128]), ALU.add, ['psGr', f'car_r{hc}'], [kGr])
        k.tt('dve', Gi_[:], psGi[:].rearrange("p (b t) -> p b t", b=4),
             car_i[:, bs].unsqueeze(2).broadcast_to([128, 4, 128]), ALU.add, ['psGi', f'car_i{hc}'], [kGi])
        gr127 = Gr_[:, :, 127]
        gi127 = Gi_[:, :, 127]
        CK = [f'cc1{hc}', f'cc2{hc}']
        k.tt('dve', cc1[hc][:], L128r[:, bs], gr127, ALU.mult, ['L128r', kGr], [CK[0]])
        k.tt('dve', cc2[hc][:], L128i[:, bs], gi127, ALU.mult, ['L128i', kGi], [CK[1]])
        k.tt('dve', car_r[:, bs], cc1[hc][:], cc2[hc][:], ALU.subtract, CK, [f'car_r{hc}'])
        k.tt('dve', cc1[hc][:], L128r[:, bs], gi127, ALU.mult, ['L128r', kGi], [CK[0]])
        k.tt('dve', cc2[hc][:], L128i[:, bs], gr127, ALU.mult, ['L128i', kGr], [CK[1]])
        k.tt('dve', car_i[:, bs], cc1[hc][:], cc2[hc][:], ALU.add, CK, [f'car_i{hc}'])
        yield
        qr = Qr[:, bs, :].rearrange("p b t -> p (b t)")
        qi = Qi[:, bs, :].rearrange("p b t -> p (b t)")
        k.tt('dve', n1_[:], fl(Gr_), qr, ALU.mult, [kGr, 'Qr'], [kn1])
        k.tt('dve', n2_[:], fl(Gi_), qi, ALU.mult, [kGi, 'Qi'], [kn2])
        k.tt('dve', n3_[:], fl(Gi_), qr, ALU.mult, [kGi, 'Qr'], [kn3])
        k.tt('dve', n4_[:], fl(Gr_), qi, ALU.mult, [kGr, 'Qi'], [kn4])
        yield
        for nb in range(4):
            blk = hc * 4 + nb
            ns = slice(nb * 128, (nb + 1) * 128)
            yo_s = psY[:, blk * 32:(blk + 1) * 32]
            k.mm(yo_s, n1_[:, ns], Crr[:, blk, :], True, False, [kn1, 'Crr'], ['psY'])
            k.mm(yo_s, n2_[:, ns], nCr[:, blk, :], False, False, [kn2, 'nCr'], ['psY'])
            k.mm(yo_s, n3_[:, ns], nCir[:, blk, :], False, False, [kn3, 'nCir'], ['psY'])
            k.mm(yo_s, n4_[:, ns], nCir[:, blk, :], False, True, [kn4, 'nCir'], ['psY'])
        yield
        k.tt('dve', yo_[:], yo_[:], psY[:, hc * 128:(hc + 1) * 128], ALU.add, [kyo, 'psY'], [kyo])
        yield
        k.dma('pool', y[rows, hc * 128:(hc + 1) * 128], yo_[:], r=[kyo], final=True)

    yield from pipeline_gen(item, 2 * NT)


def build_S5(L, k=None):
    k = k or K()
    for _ in gen_S5(L, k):
        pass
    return k.finish()


def s5_host_inputs(s, proj_u, prm):
    gs = slice(16 * s, 16 * s + 16)
    cs = slice(256 * s, 256 * s + 256)
    uc = np.ascontiguousarray(proj_u[:, cs])
    Bre = np.zeros((2, 128, 512), np.float32)
    Bim = np.zeros((2, 128, 512), np.float32)
    Cre = np.zeros((8, 128, 32), np.float32)
    Cim = np.zeros((8, 128, 32), np.float32)
    b_re, b_im = prm['s5_b_re'][gs], prm['s5_b_im'][gs]
    c_re, c_im = prm['s5_c_re'][gs], prm['s5_c_im'][gs]
    for g in range(16):
        hc, gl = g // 8, g % 8
        Bre[hc, gl * 16:(gl + 1) * 16, gl * 64:(gl + 1) * 64] = b_re[g].T
        Bim[hc, gl * 16:(gl + 1) * 16, gl * 64:(gl + 1) * 64] = b_im[g].T
        blk, g2 = g // 2, g % 2
        Cre[blk, g2 * 64:(g2 + 1) * 64, g2 * 16:(g2 + 1) * 16] = c_re[g].T
        Cim[blk, g2 * 64:(g2 + 1) * 64, g2 * 16:(g2 + 1) * 16] = c_im[g].T
    return dict(uT=np.ascontiguousarray(uc.T), u=uc,
                lam_re=np.ascontiguousarray(prm['s5_lambda_re'][gs].reshape(-1)),
                lam_im=np.ascontiguousarray(prm['s5_lambda_im'][gs].reshape(-1)),
                lstep=np.ascontiguousarray(np.repeat(prm['s5_log_step'][gs], 64)),
                Bre=Bre, Bim=Bim, Cre=Cre, Cim=Cim, dsk=np.ascontiguousarray(prm['s5_d'][cs]),
                triu=np.triu(np.ones((128, 128), np.float32)),
                iota_p=np.arange(128, dtype=np.float32).reshape(128, 1),
                iota_f=np.tile(np.arange(128, dtype=np.float32)[None], (128, 1)))


GELU_C = 1.5957691216057308


def gen_LRU(L, k):
    TT = 512
    NCH = L // TT
    xbT = k.din("xbT", [256, L])
    gateT = k.din("gateT", [256, L])
    cw_d = k.din("cw", [128, 2, 4])
    cb_d = k.din("cb", [128, 2])
    Wa_d = k.din("Wa", [2, 128, 128])
    Wx_d = k.din("Wx", [2, 128, 128])
    ba_d = k.din("ba", [128, 2])
    bx_d = k.din("bx", [128, 2])
    lam_d = k.din("lam", [128, 2])
    odT = k.dout("odT", [256, L])
    cw = k.sb("cw_s", [128, 2, 4])
    cb = k.sb("cb_s", [128, 2])
    Wa = k.sb("Wa_s", [128, 2, 128])
    Wx = k.sb("Wx_s", [128, 2, 128])
    ba = k.sb("ba_s", [128, 2])
    bx = k.sb("bx_s", [128, 2])
    c8 = k.sb("c8", [128, 2])
    k.dma('sp', cw[:], cw_d, w=['cw'])
    k.dma('sp', cb[:], cb_d, w=['cb'])
    k.dma('sp', Wa[:], Wa_d.rearrange("b p n -> p b n"), w=['Wa'])
    k.dma('sp', Wx[:], Wx_d.rearrange("b p n -> p b n"), w=['Wx'])
    k.dma('sp', ba[:], ba_d, w=['ba'])
    k.dma('sp', bx[:], bx_d, w=['bx'])
    k.dma('sp', c8[:], lam_d, w=['c8'])
    k.act(c8[:], c8[:], AF.Exp, ['c8'], ['c8'], scale=-1.0)
    k.act(c8[:], c8[:], AF.Ln, ['c8'], ['c8'], bias=1.0)
    k.ts('dve', c8[:], c8[:], -8.0, None, ALU.mult, None, ['c8'], ['c8'])
    hlast = k.sb("hlast", [128, 2])
    k.memset('dve', hlast[:], 0.0, ['hlast0', 'hlast1'])

    def ring(nm, shape, n):
        return [k.sb(f"{nm}{j}", shape) for j in range(n)]
    xh = ring("xh", [128, TT + 3], 3)
    gt = ring("gt", [128, TT], 8)
    xc = ring("xc", [128, TT], 5)
    r, ig, a, a2 = ring("r", [128, TT], 2), ring("ig", [128, TT], 3), ring("a", [128, TT], 5), ring("a2", [128, TT], 3)
    bt = ring("bt", [128, TT], 4)
    g2 = ring("g2", [128, TT], 5)
    h = ring("h", [128, TT], 2)
    ot = ring("ot", [128, TT], 3)
    psR = k.ps("psR", [128, TT])
    psI = k.ps("psI", [128, TT])

    def item(n):
        c, pb = divmod(n, 2)
        prow = slice(pb * 128, (pb + 1) * 128)
        def T(lst, nm):
            j = n % len(lst)
            return lst[j], f'{nm}{j}'
        xh_, kxh = T(xh, 'xh'); gt_, kgt = T(gt, 'gt'); xc_, kxc = T(xc, 'xc'); r_, kr = T(r, 'r'); ig_, kig = T(ig, 'ig')
        a_, ka = T(a, 'a'); a2_, ka2 = T(a2, 'a2'); bt_, kbt = T(bt, 'bt'); g2_, kg2 = T(g2, 'g2'); h_, kh = T(h, 'h'); ot_, kot = T(ot, 'ot')
        if c == 0:
            k.memset('dve', xh_[:, 0:3], 0.0, [kxh + 'h'])
            k.dma('sp', xh_[:, 3:TT + 3], xbT[prow, 0:TT], w=[kxh])
        else:
            k.dma('sp', xh_[:, 0:TT + 3], xbT[prow, c * TT - 3:(c + 1) * TT], w=[kxh, kxh + 'h'])
        k.dma('sp', gt_[:], gateT[prow, c * TT:(c + 1) * TT], w=[kgt])
        yield
        xk = [kxh, kxh + 'h']
        k.ts('dve', xc_[:], xh_[:, 3:TT + 3], cw[:, pb, 3:4], cb[:, pb:pb + 1], ALU.mult, ALU.add, xk + ['cw', 'cb'], [kxc])
        for j in (2, 1, 0):
            k.stt(xc_[:], xh_[:, j:j + TT], cw[:, pb, j:j + 1], xc_[:], ALU.mult, ALU.add, xk + ['cw', kxc], [kxc])
        yield
        k.mm(psR[:], Wa[:, pb, :], xc_[:], True, True, ['Wa', kxc], ['psR'])
        k.mm(psI[:], Wx[:, pb, :], xc_[:], True, True, ['Wx', kxc], ['psI'])
        yield
        k.act(r_[:], psR[:], AF.Sigmoid, ['psR', 'ba'], [kr], bias=ba[:, pb:pb + 1])
        k.act(ig_[:], psI[:], AF.Sigmoid, ['psI', 'bx'], [kig], bias=bx[:, pb:pb + 1])
        k.act(a_[:], r_[:], AF.Exp, [kr, 'c8'], [ka], scale=c8[:, pb:pb + 1])
        k.act(a2_[:], a_[:], AF.Square, [ka], [ka2])
        k.act(a2_[:], a2_[:], AF.Sqrt, [ka2], [ka2], scale=-1.0, bias=1.0)
        k.act(g2_[:], gt_[:], AF.Square, [kgt], [kg2])
        k.act(g2_[:], g2_[:], AF.Copy, [kg2], [kg2], scale=0.044715, bias=1.0)
        yield
        k.tt('dve', bt_[:], ig_[:], xc_[:], ALU.mult, [kig, kxc], [kbt])
        k.tt('dve', bt_[:], bt_[:], a2_[:], ALU.mult, [kbt, ka2], [kbt])
        k.tt('dve', g2_[:], g2_[:], gt_[:], ALU.mult, [kg2, kgt], [kg2])
        yield
        k.act(g2_[:], g2_[:], AF.Sigmoid, [kg2], [kg2], scale=GELU_C)
        yield
        k.P.op('dve', lambda e: e.tensor_tensor_scan(out=h_[:], data0=a_[:], data1=bt_[:], initial=hlast[:, pb:pb + 1],
                                                     op0=ALU.mult, op1=ALU.add),
               reads=[ka, kbt, f'hlast{pb}'], writes=[kh])
        k.cp('dve', hlast[:, pb:pb + 1], h_[:, TT - 1:TT], [kh], [f'hlast{pb}'])
        k.tt('dve', g2_[:], g2_[:], gt_[:], ALU.mult, [kg2, kgt], [kg2])
        k.tt('dve', ot_[:], h_[:], g2_[:], ALU.mult, [kh, kg2], [kot])
        yield
        k.dma('pool', odT[prow, c * TT:(c + 1) * TT], ot_[:], r=[kot], final=True)

    yield from pipeline_gen(item, 2 * NCH)


def build_LRU(L, k=None):
    k = k or K()
    for _ in gen_LRU(L, k):
        pass
    return k.finish()


def lru_host_inputs(s, xb, gate, prm):
    cs = slice(256 * s, 256 * s + 256)
    col = lambda v: np.ascontiguousarray(v[cs].reshape(2, 128).T)
    Wa = np.zeros((2, 128, 128), np.float32)
    Wx = np.zeros((2, 128, 128), np.float32)
    for pb in range(2):
        for bl in range(2):
            blk = 4 * s + 2 * pb + bl
            Wa[pb, bl * 64:(bl + 1) * 64, bl * 64:(bl + 1) * 64] = prm['lru_w_a'][blk]
            Wx[pb, bl * 64:(bl + 1) * 64, bl * 64:(bl + 1) * 64] = prm['lru_w_x'][blk]
    cw = np.ascontiguousarray(prm['lru_conv_w'][:, cs].reshape(4, 2, 128).transpose(2, 1, 0))
    return dict(xbT=np.ascontiguousarray(xb[:, cs].T), gateT=np.ascontiguousarray(gate[:, cs].T), cw=cw,
                cb=col(prm['lru_conv_b']), Wa=Wa, Wx=Wx, ba=col(prm['lru_b_a']), bx=col(prm['lru_b_x']),
                lam=col(prm['lru_lambda']))


GN_EPS = 64e-5
NLEV = 5


def build_RWKV(L, k=None, NH=4, fr=False, CH=64):
    k = k or K()
    NT = L // 128
    W = NH * 64
    NG = NH // 4
    FR = mybir.dt.float32r if fr else F32
    rd = (lambda ap: ap.bitcast(F32)) if fr else (lambda ap: ap)
    NCK = 128 // CH
    nlev = 5 if CH == 64 else 6
    frc = fr and CH == 128
    FRC = mybir.dt.float32r if frc else F32
    rdc = (lambda ap: ap.bitcast(F32)) if frc else (lambda ap: ap)
    lhc = (lambda ap: ap) if frc else rd
    prkv = [k.din(nm, [L, W]) for nm in ("pr", "pk", "pv")]
    mu1 = k.din("mu1", [3 * W])
    pls = [k.din("plw", [64, L]), k.din("pla", [64, L]), k.din("plg", [128, L])]
    mul = k.din("mul", [128, 3])
    w2 = k.din("w2", [64, W])
    a2 = k.din("a2", [64, W])
    g2 = k.din("g2", [128, W])
    vecs = k.din("vecs", [7, W])
    ident_d = k.din("ident", [128, 128])
    triw_d = k.din("triw", [3, 128, 128])
    mask5_d = k.din("mask5", [128, 640])
    rowm_d = k.din("rowm", [128, 2])
    oc = k.dout("oc", [L, W])

    k.consts(ident_d)
    triw = k.sb("triw_s", [128, 3, 128])
    k.dma('sp', triw[:], triw_d.rearrange("a p n -> p a n"), w=['triw'])
    mask5 = k.sb("mask5_s", [128, 640])
    k.dma('sp', mask5[:], mask5_d, w=['mask5'])
    rowm = k.sb("rowm_s", [128, 2])
    k.dma('sp', rowm[:], rowm_d, w=['rowm'])
    mu1bc = k.bcast_row("mu1bc", mu1, 3 * W)
    vb = [k.bcast_row(f"vb{i}", vecs[i], W) for i in range(7)]
    w0bc, a0bc, kkbc, kabc, rkbc, lngbc, lnbbc = vb
    VK = [f"vb{i}" for i in range(7)]
    muls = k.sb("muls", [128, 3])
    k.dma('sp', muls[:], mul, w=['muls'])
    w2s = k.sb("w2s", [64, W])
    a2s = k.sb("a2s", [64, W])
    k.dma('sp', w2s[:], w2, w=['w2s'])
    k.dma('sp', a2s[:], a2, w=['a2s'])
    g2s = k.sb("g2s", [128, W])
    k.dma('sp', g2s[:], g2, w=['g2s'])
    ST = [k.sb(f"ST{i}", [64, 64], FRC) for i in range(NH)]
    zt = k.sb("zt", [128, W])
    k.memset('dve', zt[:], 0.0, ['zt'])
    for i in range(NH):
        k.cp('dve', ST[i][:], zt[0:64, 0:64], ['zt'], [f'ST{i}'])
    P1s = k.sb("P1s", [128, W], FRC)
    Us = k.sb("Us", [128, W], FRC)
    k.cp('dve', P1s[:], zt[:], ['zt'], ['P1s'])
    k.cp('dve', Us[:], zt[:], ['zt'], ['Us'])

    pt = [k.sb(f"pt{i}", [128, 3 * W]) for i in range(2)]
    pp = [k.sb(f"pp{i}", [128, 3 * W]) for i in range(2)]
    lt = [k.sb(f"lt{i}", [128, 3, 128]) for i in range(2)]
    lp = [k.sb(f"lp{i}", [128, 3, 128]) for i in range(2)]
    for i_ in range(2):
        k.memset('pool', lt[i_][:], 0.0, [f'lt{i_}0', f'lt{i_}1', f'lt{i_}2'])
        k.memset('pool', lp[i_][:], 0.0, [f'lp{i_}0', f'lp{i_}1', f'lp{i_}2', f'lp{i_}z'])
    pm = k.sb("pm", [128, 3 * W])
    vr = k.sb("vr", [128, W], FR)
    lm = k.sb("lm", [128, 3, 128])
    sw = k.sb("sw", [128, W])
    av = k.sb("av", [128, W])
    gv = k.sb("gv", [128, W])
    kkr = k.sb("kkr", [128, W])
    sq = k.sb("sq", [128, W])
    s4 = k.sb("s4", [128, NH])
    rn = k.sb("rn", [128, NH])
    nkk = k.sb("nkk", [128, W])
    kmod = k.sb("kmod", [128, W])
    kka = k.sb("kka", [128, W])
    tmp = k.sb("tmp", [128, W])
    bon = k.sb("bon", [128, NH])
    E1 = k.sb("E1", [128, W])
    E2 = k.sb("E2", [128, W])
    E3 = k.sb("E3", [128, W])
    E4 = k.sb("E4", [128, W])
    E1T = k.sb("E1T", [64, NH, 128])
    At = k.sb("At", [128, W])
    Bs = k.sb("Bs", [128, W])
    Ks = k.sb("Ks", [128, W])
    Rt = k.sb("Rt", [128, W])
    Bfm = [k.sb(f"Bfm{c}", [128, W]) for c in range(2)]
    Kfm = [k.sb(f"Kfm{c}", [128, W]) for c in range(2)]
    FT = [k.sb(f"FT{h}", [64, 4, 128], FR) for h in range(NH)]
    A5 = [k.sb(f"A5_{h}", [128, 640], FR) for h in range(NH)]
    NL = [k.sb(f"NL_{h}", [128, 256], FR) for h in range(NH)]
    PQ = [k.sb(f"PQ_{h}", [128, 256], FR) for h in range(NH)]
    W1 = k.sb("W1", [128, W], FR)
    U1 = k.sb("U1", [128, W])
    ysb = k.sb("ysb", [128, W])
    yc = k.sb("yc", [128, W])
    m4 = k.sb("m4", [128, NH])
    r4 = k.sb("r4", [128, NH])
    ot = [k.sb(f"ot{i}", [128, W]) for i in range(2)]
    B = [k.ps(f"psB{i}", [128, 512]) for i in range(8)]
    bk = lambda i: f'psB{i}'
    v3 = lambda t: t.rearrange("p (h j) -> p h j", h=NH)
    bc4 = lambda t: t.unsqueeze(2).broadcast_to([128, NH, 64])

    for i in range(NT):
        b = i % 2
        rows = slice(i * 128, (i + 1) * 128)
        PK, PPK, LTK, LPK = [], [], [], []
        for q in range(3):
            cq = slice(q * W, (q + 1) * W)
            k.dma('sp', pt[b][:, cq], prkv[q][rows, :], w=[f'pt{b}{q}'])
            PK.append(f'pt{b}{q}')
            if i == 0:
                k.dma('sp', pp[b][1:128, cq], prkv[q][0:127, :], w=[f'pp{b}{q}'])
            else:
                k.dma('sp', pp[b][:, cq], prkv[q][i * 128 - 1:i * 128 + 127, :], w=[f'pp{b}{q}'])
            PPK.append(f'pp{b}{q}')
            nr = pls[q].shape[0]
            k.dma('sp', lt[b][0:nr, q, :], pls[q][:, rows], w=[f'lt{b}{q}'])
            LTK.append(f'lt{b}{q}')
            if i == 0:
                k.dma('sp', lp[b][0:nr, q, 1:128], pls[q][:, 0:127], w=[f'lp{b}{q}'])
            else:
                k.dma('sp', lp[b][0:nr, q, :], pls[q][:, i * 128 - 1:i * 128 + 127], w=[f'lp{b}{q}'])
            LPK.append(f'lp{b}{q}')
        if i == 0:
            k.memset('pool', pp[b][0:1, :], 0.0, [f'pp{b}z'])
            k.memset('pool', lp[b][:, :, 0:1], 0.0, [f'lp{b}z'])
            PPK.append(f'pp{b}z')
            LPK.append(f'lp{b}z')
        k.tt('pool', pm[:], pp[b][:], pt[b][:], ALU.subtract, PPK + PK, ['pm'])
        k.tt('pool', pm[:], pm[:], mu1bc[:], ALU.mult, ['pm', 'mu1bc'], ['pm'])
        k.tt('pool', pm[:], pm[:], pt[b][:], ALU.add, ['pm'] + PK, ['pm'])
        r_, k_, v_ = pm[:, 0:W], pm[:, W:2 * W], pm[:, 2 * W:3 * W]
        k.cp('act', vr[:], v_, ['pm'], ['vr'])
        LK = LTK + LPK
        k.tt('dve', lm[:], lp[b][:], lt[b][:], ALU.subtract, LK, ['lm'])
        for blk in range(3):
            k.stt(lm[:, blk, :], lm[:, blk, :], muls[:, blk:blk + 1], lt[b][:, blk, :], ALU.mult, ALU.add,
                  ['lm', 'muls'] + LK, ['lm'])
        k.act(lm[0:64, 0, :], lm[0:64, 0, :], AF.Tanh, ['lm'], ['lm'])
        k.act(lm[:, 2, :], lm[:, 2, :], AF.Sigmoid, ['lm'], ['lm'])
        k.mm(B[0][:, 0:W], lm[0:64, 0, :], w2s[:], True, True, ['lm', 'w2s'], [bk(0)])
        k.mm(B[1][:, 0:W], lm[0:64, 1, :], a2s[:], True, True, ['lm', 'a2s'], [bk(1)])
        k.mm(B[2][:, 0:W], lm[:, 2, :], g2s[:], True, True, ['lm', 'g2s'], [bk(2)])
        k.tt('dve', sw[:], B[0][:, 0:W], w0bc[:], ALU.add, [bk(0), VK[0]], ['sw'])
        k.act(sw[:], sw[:], AF.Sigmoid, ['sw'], ['sw'])
        k.tt('dve', av[:], B[1][:, 0:W], a0bc[:], ALU.add, [bk(1), VK[1]], ['av'])
        k.act(av[:], av[:], AF.Sigmoid, ['av'], ['av'])
        k.cp('act', gv[:], B[2][:, 0:W], [bk(2)], ['gv'])
        k.tt('pool', kkr[:], k_, kkbc[:], ALU.mult, ['pm', VK[2]], ['kkr'])
        k.tt('pool', sq[:], kkr[:], kkr[:], ALU.mult, ['kkr'], ['sq'])
        k.P.op('dve', lambda e: e.tensor_reduce(out=s4[:], in_=v3(sq[:]), axis=AX.X, op=ALU.add), reads=['sq'], writes=['s4'])
        k.act(s4[:], s4[:], AF.Sqrt, ['s4'], ['s4'])
        k.ts('dve', s4[:], s4[:], 1e-12, None, ALU.max, None, ['s4'], ['s4'])
        k.recip(rn[:], s4[:], ['s4'], ['rn'])
        k.ts('dve', rn[:], rn[:], -1.0, None, ALU.mult, None, ['rn'], ['rn'])
        k.tt('dve', v3(nkk[:]), v3(kkr[:]), bc4(rn[:]), ALU.mult, ['kkr', 'rn'], ['nkk'])
        k.stt(tmp[:], av[:], -1.0, kabc[:], ALU.add, ALU.mult, ['av', VK[3]], ['tmp'])
        k.stt(kmod[:], tmp[:], 1.0, k_, ALU.add, ALU.mult, ['tmp', 'pm'], ['kmod'])
        k.stt(kka[:], nkk[:], -1.0, av[:], ALU.mult, ALU.mult, ['nkk', 'av'], ['kka'])
        k.tt('pool', tmp[:], r_, kmod[:], ALU.mult, ['pm', 'kmod', 'tmp'], ['tmp'])
        k.tt('pool', tmp[:], tmp[:], rkbc[:], ALU.mult, ['tmp', VK[4]], ['tmp'])
        k.P.op('dve', lambda e: e.tensor_reduce(out=bon[:], in_=v3(tmp[:]), axis=AX.X, op=ALU.add), reads=['tmp'], writes=['bon'])
        k.mm(B[3][:, 0:W], triw[:, 0, :], sw[:], True, True, ['triw', 'sw'], [bk(3)])
        k.mm(B[4][:, 0:W], triw[:, 1, :], sw[:], True, True, ['triw', 'sw'], [bk(4)])
        k.mm(B[5][:, 0:W], triw[:, 2, :], sw[:], True, True, ['triw', 'sw'], [bk(5)])
        for h in range(NH):
            k.mm(B[6 + h // 4][0:64, (h % 4) * 128:(h % 4 + 1) * 128], sw[:, h * 64:(h + 1) * 64], triw[:, 0, :], True, True,
                 ['sw', 'triw'], [bk(6 + h // 4)])
        k.act(E1[:], B[3][:, 0:W], AF.Exp, [bk(3)], ['E1'])
        k.act(E2[:], B[3][:, 0:W], AF.Exp, [bk(3)], ['E2'], scale=-1.0)
        k.act(E3[:], B[4][:, 0:W], AF.Exp, [bk(4)], ['E3'])
        k.act(E4[:], B[5][:, 0:W], AF.Exp, [bk(5)], ['E4'])
        for g in range(NG):
            k.act(E1T[:, 4 * g:4 * g + 4, :].rearrange("p a t -> p (a t)"), B[6 + g][0:64, :], AF.Exp, [bk(6 + g)], ['E1T'])
        k.tt('dve', At[:], nkk[:], E3[:], ALU.mult, ['nkk', 'E3'], ['At'])
        k.tt('pool', Bs[:], kka[:], E2[:], ALU.mult, ['kka', 'E2'], ['Bs'])
        k.tt('dve', Ks[:], kmod[:], E2[:], ALU.mult, ['kmod', 'E2'], ['Ks'])
        k.tt('pool', Rt[:], r_, E1[:], ALU.mult, ['pm', 'E1'], ['Rt'])
        for c in range(NCK):
            k.stt(Bfm[c][:], kka[:], rowm[:, c:c + 1], E4[:], ALU.mult, ALU.mult, ['kka', 'E4', 'rowm'], [f'Bfm{c}'])
            k.stt(Kfm[c][:], kmod[:], rowm[:, c:c + 1], E4[:], ALU.mult, ALU.mult, ['kmod', 'E4', 'rowm'], [f'Kfm{c}'])
        HS = list(range(NH))
        for h in HS:
            cs_ = slice(h * 64, (h + 1) * 64)
            for q, (src, key) in enumerate([(At, 'At'), (Bs, 'Bs'), (Ks, 'Ks'), (Rt, 'Rt')]):
                k.tr(B[h][0:64, q * 128:(q + 1) * 128], src[:, cs_], k.identf[:], [key], [bk(h)])
        for h in HS:
            k.cp('act' if h % 2 else 'dve', FT[h][:].rearrange("p a t -> p (a t)"), B[h][0:64, :], [bk(h)], [f'FT{h}'])
        for h in HS:
            AtT, BsT, KsT, RtT = (FT[h][:, q, :] for q in range(4))
            o = lambda j: B[h][:, j * 128:(j + 1) * 128]
            k.mm(o(0), BsT, AtT, True, True, [f'FT{h}'], [bk(h)])
            k.mm(o(1), AtT, BsT, True, True, [f'FT{h}'], [bk(h)])
            k.mm(o(2), KsT, AtT, True, True, [f'FT{h}'], [bk(h)])
        for h in HS:
            k.tt('dve', A5[h][:, 0:384], B[h][:, 0:384], mask5[:, 0:384], ALU.mult, [bk(h), 'mask5'], [f'A5_{h}'])
        for h in HS:
            AtT, BsT, KsT, RtT = (FT[h][:, q, :] for q in range(4))
            k.mm(B[h][:, 0:128], BsT, RtT, True, True, [f'FT{h}'], [bk(h)])
            k.mm(B[h][:, 128:256], KsT, RtT, True, True, [f'FT{h}'], [bk(h)])
        for h in HS:
            k.tt('dve', A5[h][:, 384:640], B[h][:, 0:256], mask5[:, 384:640], ALU.mult, [bk(h), 'mask5'], [f'A5b_{h}'])
            k.cp('act', NL[h][:], rd(A5[h][:, 0:256]), [f'A5_{h}'], [f'NL_{h}'])
            k.tt('pool' if not fr else 'dve', PQ[h][:].rearrange("p (a n) -> p a n", a=2), rd(A5[h][:, 0:256]).rearrange("p (a n) -> p a n", a=2),
                 k.identf[:].unsqueeze(1).broadcast_to([128, 2, 128]), ALU.add, [f'A5_{h}', 'ident'], [f'PQ_{h}'])
        for lev in range(nlev):
            for h in HS:
                N_, L_ = NL[h][:, 0:128], NL[h][:, 128:256]
                k.mm(B[h][:, 0:128], L_, N_, True, True, [f'NL_{h}'], [bk(h)])
                k.mm(B[h][:, 128:256], N_, L_, True, True, [f'NL_{h}'], [bk(h)])
            for h in HS:
                k.cp('act', NL[h][:], B[h][:, 0:256], [bk(h)], [f'NL_{h}'])
            for h in HS:
                N_, L_ = NL[h][:, 0:128], NL[h][:, 128:256]
                P_, Q_ = PQ[h][:, 0:128], PQ[h][:, 128:256]
                k.mm(B[h][:, 256:384], Q_, N_, True, True, [f'NL_{h}', f'PQ_{h}'], [bk(h)])
                k.mm(B[h][:, 384:512], P_, L_, True, True, [f'NL_{h}', f'PQ_{h}'], [bk(h)])
            for h in HS:
                k.tt('dve', PQ[h][:], B[h][:, 256:512], rd(PQ[h][:]), ALU.add, [bk(h), f'PQ_{h}'], [f'PQ_{h}'])
        for h in range(NH):
            k.mm(B[0][:, h * 64:(h + 1) * 64], A5[h][:, 256:384], vr[:, h * 64:(h + 1) * 64], True, True, [f'A5_{h}', 'vr'], [bk(0)])
        k.cp('act', W1[:], B[0][:, 0:W], [bk(0)], ['W1'])
        for h in range(NH):
            k.mm(B[1][:, h * 64:(h + 1) * 64], PQ[h][:, 0:128], W1[:, h * 64:(h + 1) * 64], True, True,
                 [f'PQ_{h}', 'W1'], [bk(1)])
        k.cp('act', U1[:], B[1][:, 0:W], [bk(1)], ['U1'])
        vsrc = vr if frc else None
        for c in range(NCK):
            cr = slice(c * CH, (c + 1) * CH)
            for h in range(NH):
                k.mm(B[2][cr, h * 64:(h + 1) * 64], lhc(FT[h][:, 0, cr]), ST[h][:], True, True, [f'FT{h}', f'ST{h}'], [bk(2)])
            k.cp('act', P1s[cr, :], B[2][cr, 0:W], [bk(2)], ['P1s'])
            for h in range(NH):
                k.mm(B[3][cr, h * 64:(h + 1) * 64], lhc(PQ[h][:, cr]), P1s[:, h * 64:(h + 1) * 64], True, True,
                     [f'PQ_{h}', 'P1s'], [bk(3)])
            k.tt('dve', Us[cr, :], B[3][cr, 0:W], U1[cr, :], ALU.add, [bk(3), 'U1'], ['Us'])
            for h in range(NH):
                hc_ = slice(h * 64, (h + 1) * 64)
                vh = vr[:, hc_] if frc else pm[:, 2 * W + h * 64:2 * W + (h + 1) * 64]
                vk = 'vr' if frc else 'pm'
                k.mm(B[6][cr, hc_], lhc(FT[h][:, 3, cr]), ST[h][:], True, False, [f'FT{h}', f'ST{h}'], [bk(6)])
                k.mm(B[6][cr, hc_], lhc(A5[h][:, 384:512][:, cr]), Us[:, hc_], False, False, [f'A5b_{h}', 'Us'], [bk(6)])
                k.mm(B[6][cr, hc_], lhc(A5[h][:, 512:640][:, cr]), vh, False, True, [f'A5b_{h}', vk], [bk(6)])
            for h in range(NH):
                hc_ = slice(h * 64, (h + 1) * 64)
                vh = pm[:, 2 * W + h * 64:2 * W + (h + 1) * 64]
                k.mm(B[7][0:64, hc_], Bfm[c][:, hc_], rdc(Us[:, hc_]), True, False, [f'Bfm{c}', 'Us'], [bk(7)])
                k.mm(B[7][0:64, hc_], Kfm[c][:, hc_], vh, False, True, [f'Kfm{c}', 'pm'], [bk(7)])
            for h in range(NH):
                hc_ = slice(h * 64, (h + 1) * 64)
                k.stt(ST[h][:], rdc(ST[h][:]), E1T[:, h, (c + 1) * CH - 1:(c + 1) * CH], B[7][0:64, hc_], ALU.mult, ALU.add,
                      [f'ST{h}', 'E1T', bk(7)], [f'ST{h}'])
        k.cp('act', ysb[:], B[6][:, 0:W], [bk(6)], ['ysb'])
        k.P.op('dve', lambda e: e.tensor_reduce(out=m4[:], in_=v3(ysb[:]), axis=AX.X, op=ALU.add), reads=['ysb'], writes=['m4'])
        k.ts('dve', m4[:], m4[:], -1.0 / 64.0, None, ALU.mult, None, ['m4'], ['m4'])
        k.tt('dve', v3(yc[:]), v3(ysb[:]), bc4(m4[:]), ALU.add, ['ysb', 'm4'], ['yc'])
        k.tt('pool', sq[:], yc[:], yc[:], ALU.mult, ['yc'], ['sq'])
        k.P.op('dve', lambda e: e.tensor_reduce(out=r4[:], in_=v3(sq[:]), axis=AX.X, op=ALU.add), reads=['sq'], writes=['r4'])
        k.ts('dve', r4[:], r4[:], 1.0 / 64.0, GN_EPS, ALU.mult, ALU.add, ['r4'], ['r4'])
        k.act(r4[:], r4[:], AF.Sqrt, ['r4'], ['r4'])
        k.recip(r4[:], r4[:], ['r4'], ['r4'])
        k.tt('dve', v3(yc[:]), v3(yc[:]), bc4(r4[:]), ALU.mult, ['yc', 'r4'], ['yc'])
        k.tt('pool', yc[:], yc[:], lngbc[:], ALU.mult, ['yc', VK[5]], ['yc'])
        k.tt('pool', yc[:], yc[:], lnbbc[:], ALU.add, ['yc', VK[6]], ['yc'])
        k.tt('dve', v3(tmp[:]), v3(v_), bc4(bon[:]), ALU.mult, ['pm', 'bon', 'tmp'], ['tmp'])
        k.tt('pool', yc[:], yc[:], tmp[:], ALU.add, ['yc', 'tmp'], ['yc'])
        k.tt('dve', ot[b][:], yc[:], gv[:], ALU.mult, ['yc', 'gv'], [f'ot{b}'])
        k.dma('pool', oc[rows, :], ot[b][:], r=[f'ot{b}'], final=True)
    return k.finish()


def build_RWKVP(L, k=None, CH=64):
    NH, fr = 8, True
    k = k or K()
    NT = L // 128
    W = NH * 64
    NG = NH // 4
    FR = mybir.dt.float32r if fr else F32
    rd = (lambda ap: ap.bitcast(F32)) if fr else (lambda ap: ap)
    NCK = 128 // CH
    nlev = 5 if CH == 64 else 6
    frc = True
    FRC = mybir.dt.float32r if frc else F32
    rdc = (lambda ap: ap.bitcast(F32)) if frc else (lambda ap: ap)
    lhc = (lambda ap: ap) if frc else rd
    prkv = [k.din(nm, [L, W]) for nm in ("pr", "pk", "pv")]
    mu1 = k.din("mu1", [3 * W])
    pls = [k.din("plw", [64, L]), k.din("pla", [64, L]), k.din("plg", [128, L])]
    mul = k.din("mul", [128, 3])
    w2 = k.din("w2", [64, W])
    a2 = k.din("a2", [64, W])
    g2 = k.din("g2", [128, W])
    vecs = k.din("vecs", [7, W])
    ident_d = k.din("ident", [128, 128])
    triw_d = k.din("triw", [3, 128, 128])
    mask5_d = k.din("mask5", [128, 640])
    rowm_d = k.din("rowm", [128, 2])
    oc = k.dout("oc", [L, W])

    k.consts(ident_d)
    triw = k.sb("triw_s", [128, 3, 128])
    k.dma('sp', triw[:], triw_d.rearrange("a p n -> p a n"), w=['triw'])
    mask5 = k.sb("mask5_s", [128, 640])
    k.dma('sp', mask5[:], mask5_d, w=['mask5'])
    rowm = k.sb("rowm_s", [128, 2])
    k.dma('sp', rowm[:], rowm_d, w=['rowm'])
    mu1bc = k.bcast_row("mu1bc", mu1, 3 * W)
    vb = [k.bcast_row(f"vb{i}", vecs[i], W) for i in range(7)]
    w0bc, a0bc, kkbc, kabc, rkbc, lngbc, lnbbc = vb
    VK = [f"vb{i}" for i in range(7)]
    muls = k.sb("muls", [128, 3])
    k.dma('sp', muls[:], mul, w=['muls'])
    w2s = k.sb("w2s", [64, W])
    a2s = k.sb("a2s", [64, W])
    k.dma('sp', w2s[:], w2, w=['w2s'])
    k.dma('sp', a2s[:], a2, w=['a2s'])
    g2s = k.sb("g2s", [128, W])
    k.dma('sp', g2s[:], g2, w=['g2s'])
    ST = [k.sb(f"ST{i}", [64, 64], FRC) for i in range(NH)]
    zt = k.sb("zt", [128, W])
    k.memset('dve', zt[:], 0.0, ['zt'])
    for i in range(NH):
        k.cp('dve', ST[i][:], zt[0:64, 0:64], ['zt'], [f'ST{i}'])
    P1s = k.sb("P1s", [128, W], FRC)
    Us = k.sb("Us", [128, W], FRC)
    k.cp('dve', P1s[:], zt[:], ['zt'], ['P1s'])
    k.cp('dve', Us[:], zt[:], ['zt'], ['Us'])

    pt = [k.sb("pt0", [128, 3 * W])] * 2
    pp = [k.sb("pp0", [128, 3 * W])] * 2
    lt = [k.sb("lt0", [128, 3, 128])] * 2
    lp = [k.sb("lp0", [128, 3, 128])] * 2
    k.memset('pool', lt[0][:], 0.0, ['lt0', 'lt1', 'lt2'])
    k.memset('pool', lp[0][:], 0.0, ['lp0', 'lp1', 'lp2', 'lpz'])
    pm2 = [k.sb(f"pm{i_}", [128, 3 * W]) for i_ in range(2)]
    vr2 = [k.sb(f"vr{i_}", [128, W], FR) for i_ in range(2)]
    lm2 = [k.sb(f"lm{i_}", [128, 3, 128]) for i_ in range(2)]
    sw = k.sb("sw", [128, W])
    av = k.sb("av", [128, W])
    gv2 = [k.sb(f"gv{i_}", [128, W]) for i_ in range(2)]
    kkr = k.sb("kkr", [128, W])
    sq = k.sb("sq", [128, W])
    s4 = k.sb("s4", [128, NH])
    rn = k.sb("rn", [128, NH])
    nkk = k.sb("nkk", [128, W])
    kmod = k.sb("kmod", [128, W])
    kka = k.sb("kka", [128, W])
    tmp = k.sb("tmp", [128, W])
    bon2 = [k.sb(f"bon{i_}", [128, NH]) for i_ in range(2)]
    E1 = k.sb("E1", [128, W])
    E2 = k.sb("E2", [128, W])
    E3 = k.sb("E3", [128, W])
    E4 = k.sb("E4", [128, W])
    E1T2 = [k.sb(f"E1T{i_}", [64, NH, 128]) for i_ in range(2)]
    At2 = [k.sb(f"At{i_}", [128, W]) for i_ in range(2)]
    Bs2 = [k.sb(f"Bs{i_}", [128, W]) for i_ in range(2)]
    Ks2 = [k.sb(f"Ks{i_}", [128, W]) for i_ in range(2)]
    Rt2 = [k.sb(f"Rt{i_}", [128, W]) for i_ in range(2)]
    Bfm2 = [[k.sb(f"Bfm{p_}{c}", [128, W]) for c in range(NCK)] for p_ in range(2)]
    Kfm2 = [[k.sb(f"Kfm{p_}{c}", [128, W]) for c in range(NCK)] for p_ in range(2)]
    sqp = k.sb("sqp", [128, W])
    tmpp = k.sb("tmpp", [128, W])
    FT = [k.sb(f"FT{h}", [64, 4, 128], FR) for h in range(NH)]
    A5 = [k.sb(f"A5_{h}", [128, 640], FR) for h in range(NH)]
    NL = [k.sb(f"NL_{h}", [128, 256], FR) for h in range(NH)]
    PQ = [k.sb(f"PQ_{h}", [128, 128], FR) for h in range(NH)]
    W1 = k.sb("W1", [128, W], FR)
    U1 = k.sb("U1", [128, W])
    ysb = k.sb("ysb", [128, W])
    yc = k.sb("yc", [128, W])
    m4 = k.sb("m4", [128, NH])
    r4 = k.sb("r4", [128, NH])
    ot = [k.sb(f"ot{i}", [128, W]) for i in range(2)]
    B = [k.ps(f"psB{i}", [128, 512]) for i in range(8)]
    bk = lambda i: f'psB{i}'
    v3 = lambda t: t.rearrange("p (h j) -> p h j", h=NH)
    bc4 = lambda t: t.unsqueeze(2).broadcast_to([128, NH, 64])


    S0, S1, C0, C1 = 6, 7, 4, 5

    def tile(i):
        b = i % 2
        pm, lm = pm2[b], lm2[b]
        kpm, klm = f'pm{b}', f'lm{b}'
        At, Bs, Ks, Rt, gv, vr, bon, E1T, Bf, Kf = At2[b], Bs2[b], Ks2[b], Rt2[b], gv2[b], vr2[b], bon2[b], E1T2[b], Bfm2[b], Kfm2[b]
        kAt, kBs, kKs, kRt, kgv, kvr, kbon, kE1T, kBf, kKf = (f'{n_}{b}' for n_ in ('At', 'Bs', 'Ks', 'Rt', 'gv', 'vr', 'bon', 'E1T', 'Bf', 'Kf'))
        rows = slice(i * 128, (i + 1) * 128)
        PK, PPK, LTK, LPK = [], [], [], []
        for q in range(3):
            cq = slice(q * W, (q + 1) * W)
            k.dma('sp', pt[b][:, cq], prkv[q][rows, :], w=[f'pt{q}'])
            PK.append(f'pt{q}')
            if i == 0:
                k.dma('sp', pp[b][1:128, cq], prkv[q][0:127, :], w=[f'pp{q}'])
            else:
                k.dma('sp', pp[b][:, cq], prkv[q][i * 128 - 1:i * 128 + 127, :], w=[f'pp{q}'])
            PPK.append(f'pp{q}')
            nr = pls[q].shape[0]
            k.dma('sp', lt[b][0:nr, q, :], pls[q][:, rows], w=[f'lt{q}'])
            LTK.append(f'lt{q}')
            if i == 0:
                k.dma('sp', lp[b][0:nr, q, 1:128], pls[q][:, 0:127], w=[f'lp{q}'])
            else:
                k.dma('sp', lp[b][0:nr, q, :], pls[q][:, i * 128 - 1:i * 128 + 127], w=[f'lp{q}'])
            LPK.append(f'lp{q}')
        if i == 0:
            k.memset('pool', pp[b][0:1, :], 0.0, ['ppz'])
            k.memset('pool', lp[b][:, :, 0:1], 0.0, ['lpz'])
            PPK.append('ppz')
            LPK.append('lpz')
        k.tt('dve', pm[:], pp[b][:], pt[b][:], ALU.subtract, PPK + PK, [kpm])
        k.tt('dve', pm[:], pm[:], mu1bc[:], ALU.mult, [kpm, 'mu1bc'], [kpm])
        k.tt('dve', pm[:], pm[:], pt[b][:], ALU.add, [kpm] + PK, [kpm])
        r_, k_, v_ = pm[:, 0:W], pm[:, W:2 * W], pm[:, 2 * W:3 * W]
        LK = LTK + LPK
        k.tt('dve', lm[:], lp[b][:], lt[b][:], ALU.subtract, LK, [klm])
        for blk in range(3):
            k.stt(lm[:, blk, :], lm[:, blk, :], muls[:, blk:blk + 1], lt[b][:, blk, :], ALU.mult, ALU.add,
                  [klm, 'muls'] + LK, [klm])
        k.act(lm[0:64, 0, :], lm[0:64, 0, :], AF.Tanh, [klm], [klm])
        k.act(lm[:, 2, :], lm[:, 2, :], AF.Sigmoid, [klm], [klm])
        yield 'STAGE'
        k.cp('act', vr[:], v_, [kpm], [kvr])
        k.mm(B[S0][:, 0:W], lm[0:64, 0, :], w2s[:], True, True, [klm, 'w2s'], [bk(S0)])
        k.mm(B[S1][:, 0:W], lm[0:64, 1, :], a2s[:], True, True, [klm, 'a2s'], [bk(S1)])
        yield 'sub'
        k.tt('dve', sw[:], B[S0][:, 0:W], w0bc[:], ALU.add, [bk(S0), VK[0]], ['sw'])
        k.act(sw[:], sw[:], AF.Sigmoid, ['sw'], ['sw'])
        k.tt('dve', av[:], B[S1][:, 0:W], a0bc[:], ALU.add, [bk(S1), VK[1]], ['av'])
        k.act(av[:], av[:], AF.Sigmoid, ['av'], ['av'])
        yield 'sub'
        k.mm(B[S0][:, 0:W], lm[:, 2, :], g2s[:], True, True, [klm, 'g2s'], [bk(S0)])
        k.cp('act', gv[:], B[S0][:, 0:W], [bk(S0)], [kgv])
        yield 'sub'
        k.tt('dve', kkr[:], k_, kkbc[:], ALU.mult, [kpm, VK[2]], ['kkr'])
        k.tt('dve', sq[:], kkr[:], kkr[:], ALU.mult, ['kkr'], ['sq'])
        k.P.op('dve', lambda e: e.tensor_reduce(out=s4[:], in_=v3(sq[:]), axis=AX.X, op=ALU.add), reads=['sq'], writes=['s4'])
        k.act(s4[:], s4[:], AF.Sqrt, ['s4'], ['s4'])
        k.ts('dve', s4[:], s4[:], 1e-12, None, ALU.max, None, ['s4'], ['s4'])
        k.recip(rn[:], s4[:], ['s4'], ['rn'])
        k.ts('dve', rn[:], rn[:], -1.0, None, ALU.mult, None, ['rn'], ['rn'])
        k.tt('dve', v3(nkk[:]), v3(kkr[:]), bc4(rn[:]), ALU.mult, ['kkr', 'rn'], ['nkk'])
        k.stt(tmp[:], av[:], -1.0, kabc[:], ALU.add, ALU.mult, ['av', VK[3]], ['tmp'])
        k.stt(kmod[:], tmp[:], 1.0, k_, ALU.add, ALU.mult, ['tmp', kpm], ['kmod'])
        k.stt(kka[:], nkk[:], -1.0, av[:], ALU.mult, ALU.mult, ['nkk', 'av'], ['kka'])
        k.tt('dve', tmp[:], r_, kmod[:], ALU.mult, [kpm, 'kmod', 'tmp'], ['tmp'])
        k.tt('dve', tmp[:], tmp[:], rkbc[:], ALU.mult, ['tmp', VK[4]], ['tmp'])
        k.P.op('dve', lambda e: e.tensor_reduce(out=bon[:], in_=v3(tmp[:]), axis=AX.X, op=ALU.add), reads=['tmp'], writes=[kbon])
        yield 'sub'
        k.mm(B[S1][:, 0:W], triw[:, 0, :], sw[:], True, True, ['triw', 'sw'], [bk(S1)])
        k.mm(B[S0][:, 0:W], triw[:, 1, :], sw[:], True, True, ['triw', 'sw'], [bk(S0)])
        yield 'sub'
        k.act(E1[:], B[S1][:, 0:W], AF.Exp, [bk(S1)], ['E1'])
        k.act(E2[:], B[S1][:, 0:W], AF.Exp, [bk(S1)], ['E2'], scale=-1.0)
        k.act(E3[:], B[S0][:, 0:W], AF.Exp, [bk(S0)], ['E3'])
        k.mm(B[S1][:, 0:W], triw[:, 2, :], sw[:], True, True, ['triw', 'sw'], [bk(S1)])
        k.act(E4[:], B[S1][:, 0:W], AF.Exp, [bk(S1)], ['E4'])
        yield 'sub'
        for g in range(2):
            for hl in range(4):
                h = 4 * g + hl
                k.mm(B[S0 + g][0:64, hl * 128:(hl + 1) * 128], sw[:, h * 64:(h + 1) * 64], triw[:, 0, :], True, True,
                     ['sw', 'triw'], [bk(S0 + g)])
        yield 'sub'
        for g in range(2):
            k.act(E1T[:, 4 * g:4 * g + 4, :].rearrange("p a t -> p (a t)"), B[S0 + g][0:64, :], AF.Exp, [bk(S0 + g)], [kE1T])
        yield 'sub'
        k.tt('dve', At[:], nkk[:], E3[:], ALU.mult, ['nkk', 'E3'], [kAt])
        k.tt('dve', Bs[:], kka[:], E2[:], ALU.mult, ['kka', 'E2'], [kBs])
        k.tt('dve', Ks[:], kmod[:], E2[:], ALU.mult, ['kmod', 'E2'], [kKs])
        k.tt('dve', Rt[:], r_, E1[:], ALU.mult, [kpm, 'E1'], [kRt])
        for c in range(NCK):
            k.stt(Bf[c][:], kka[:], rowm[:, c:c + 1], E4[:], ALU.mult, ALU.mult, ['kka', 'E4', 'rowm'], [kBf])
            k.stt(Kf[c][:], kmod[:], rowm[:, c:c + 1], E4[:], ALU.mult, ALU.mult, ['kmod', 'E4', 'rowm'], [kKf])
        yield 'STAGE'
        for g in range(2):
            HS = list(range(4 * g, 4 * g + 4))
            for h in HS:
                hl = h % 4
                cs_ = slice(h * 64, (h + 1) * 64)
                for q, (src, key) in enumerate([(At, kAt), (Bs, kBs), (Ks, kKs), (Rt, kRt)]):
                    k.tr(B[hl][0:64, q * 128:(q + 1) * 128], src[:, cs_], k.identf[:], [key], [bk(hl)])
            for h in HS:
                hl = h % 4
                k.cp('act' if h % 2 else 'dve', FT[h][:].rearrange("p a t -> p (a t)"), B[hl][0:64, :], [bk(hl)], [f'FT{h}'])
            for h in HS:
                hl = h % 4
                AtT, BsT, KsT, RtT = (FT[h][:, q, :] for q in range(4))
                k.mm(B[hl][:, 0:128], BsT, AtT, True, True, [f'FT{h}'], [bk(hl)])
                k.mm(B[hl][:, 128:256], AtT, BsT, True, True, [f'FT{h}'], [bk(hl)])
                k.mm(B[hl][:, 256:384], KsT, AtT, True, True, [f'FT{h}'], [bk(hl)])
            for h in HS:
                hl = h % 4
                k.tt('dve', A5[h][:, 0:384], B[hl][:, 0:384], mask5[:, 0:384], ALU.mult, [bk(hl), 'mask5'], [f'A5_{h}'])
            for h in HS:
                hl = h % 4
                AtT, BsT, KsT, RtT = (FT[h][:, q, :] for q in range(4))
                k.mm(B[hl][:, 0:128], BsT, RtT, True, True, [f'FT{h}'], [bk(hl)])
                k.mm(B[hl][:, 128:256], KsT, RtT, True, True, [f'FT{h}'], [bk(hl)])
            for h in HS:
                hl = h % 4
                k.tt('dve', A5[h][:, 384:640], B[hl][:, 0:256], mask5[:, 384:640], ALU.mult, [bk(hl), 'mask5'], [f'A5b_{h}'])
                k.cp('act', NL[h][:], rd(A5[h][:, 0:256]), [f'A5_{h}'], [f'NL_{h}'])
                k.tt('dve', PQ[h][:, 0:128], rd(A5[h][:, 0:128]), k.identf[:], ALU.add, [f'A5_{h}', 'ident'], [f'PQ_{h}'])
            for lev in range(nlev):
                last = (lev == nlev - 1)
                for h in HS:
                    hl = h % 4
                    N_, L_ = NL[h][:, 0:128], NL[h][:, 128:256]
                    k.mm(B[hl][:, 0:128], L_, N_, True, True, [f'NL_{h}'], [bk(hl)])
                    k.mm(B[hl][:, 128:256], N_, L_, True, True, [f'NL_{h}'], [bk(hl)])
                for h in HS:
                    hl = h % 4
                    k.cp('act', NL[h][:], B[hl][:, 0:256], [bk(hl)], [f'NL_{h}'])
                for h in HS:
                    hl = h % 4
                    k.mm(B[hl][:, 256:384], NL[h][:, 128:256], PQ[h][:, 0:128], True, True, [f'NL_{h}', f'PQ_{h}'], [bk(hl)])
                for h in HS:
                    hl = h % 4
                    k.tt('dve', PQ[h][:, 0:128], B[hl][:, 256:384], rd(PQ[h][:, 0:128]), ALU.add, [bk(hl), f'PQ_{h}'], [f'PQ_{h}'])
            yield 'GROUP'
        for h in range(NH):
            k.mm(B[C0][:, h * 64:(h + 1) * 64], A5[h][:, 256:384], vr[:, h * 64:(h + 1) * 64], True, True, [f'A5_{h}', kvr], [bk(C0)])
        yield 'sub'
        k.cp('act', W1[:], B[C0][:, 0:W], [bk(C0)], ['W1'])
        yield 'sub'
        for h in range(NH):
            k.mm(B[C1][:, h * 64:(h + 1) * 64], PQ[h][:, 0:128], W1[:, h * 64:(h + 1) * 64], True, True,
                 [f'PQ_{h}', 'W1'], [bk(C1)])
        yield 'sub'
        k.cp('act', U1[:], B[C1][:, 0:W], [bk(C1)], ['U1'])
        yield 'sub'
        for c in range(NCK):
            cr = slice(c * CH, (c + 1) * CH)
            for h in range(NH):
                k.mm(B[C0][:, h * 64:(h + 1) * 64], FT[h][:, 0, :], ST[h][:], True, True, [f'FT{h}', f'ST{h}'], [bk(C0)])
            yield 'sub'
            k.cp('act', P1s[cr, :], B[C0][cr, 0:W], [bk(C0)], ['P1s'])
            yield 'sub'
            for h in range(NH):
                k.mm(B[C0][:, h * 64:(h + 1) * 64], PQ[h][:, :], P1s[:, h * 64:(h + 1) * 64], True, True,
                     [f'PQ_{h}', 'P1s'], [bk(C0)])
            yield 'sub'
            k.tt('dve', Us[cr, :], B[C0][cr, 0:W], U1[cr, :], ALU.add, [bk(C0), 'U1'], ['Us'])
            yield 'sub'
            for h in range(NH):
                hc_ = slice(h * 64, (h + 1) * 64)
                k.mm(B[C0][:, hc_], FT[h][:, 3, :], ST[h][:], True, False, [f'FT{h}', f'ST{h}'], [bk(C0)])
                k.mm(B[C0][:, hc_], A5[h][:, 384:512], Us[:, hc_], False, False, [f'A5b_{h}', 'Us'], [bk(C0)])
                k.mm(B[C0][:, hc_], A5[h][:, 512:640], vr[:, hc_], False, True, [f'A5b_{h}', kvr], [bk(C0)])
            yield 'sub'
            k.cp('act', ysb[cr, :], B[C0][cr, 0:W], [bk(C0)], ['ysb'])
            for h in range(NH):
                hc_ = slice(h * 64, (h + 1) * 64)
                k.mm(B[C1][0:64, hc_], Bf[c][:, hc_], rdc(Us[:, hc_]), True, False, [kBf, 'Us'], [bk(C1)])
                k.mm(B[C1][0:64, hc_], Kf[c][:, hc_], rd(vr[:, hc_]), False, True, [kKf, kvr], [bk(C1)])
            yield 'sub'
            for h in range(NH):
                hc_ = slice(h * 64, (h + 1) * 64)
                k.stt(ST[h][:], rdc(ST[h][:]), E1T[:, h, (c + 1) * CH - 1:(c + 1) * CH], B[C1][0:64, hc_], ALU.mult, ALU.add,
                      [f'ST{h}', kE1T, bk(C1)], [f'ST{h}'])
        k.P.op('dve', lambda e: e.tensor_reduce(out=m4[:], in_=v3(ysb[:]), axis=AX.X, op=ALU.add), reads=['ysb'], writes=['m4'])
        k.ts('dve', m4[:], m4[:], -1.0 / 64.0, None, ALU.mult, None, ['m4'], ['m4'])
        k.tt('dve', v3(yc[:]), v3(ysb[:]), bc4(m4[:]), ALU.add, ['ysb', 'm4'], ['yc'])
        k.tt('dve', sqp[:], yc[:], yc[:], ALU.mult, ['yc'], ['sqp'])
        k.P.op('dve', lambda e: e.tensor_reduce(out=r4[:], in_=v3(sqp[:]), axis=AX.X, op=ALU.add), reads=['sqp'], writes=['r4'])
        k.ts('dve', r4[:], r4[:], 1.0 / 64.0, GN_EPS, ALU.mult, ALU.add, ['r4'], ['r4'])
        k.act(r4[:], r4[:], AF.Sqrt, ['r4'], ['r4'])
        k.recip(r4[:], r4[:], ['r4'], ['r4'])
        k.tt('dve', v3(yc[:]), v3(yc[:]), bc4(r4[:]), ALU.mult, ['yc', 'r4'], ['yc'])
        k.tt('dve', yc[:], yc[:], lngbc[:], ALU.mult, ['yc', VK[5]], ['yc'])
        k.tt('dve', yc[:], yc[:], lnbbc[:], ALU.add, ['yc', VK[6]], ['yc'])
        k.tt('dve', v3(tmpp[:]), v3(rd(vr[:])), bc4(bon[:]), ALU.mult, [kvr, kbon], ['tmpp'])
        k.tt('dve', yc[:], yc[:], tmpp[:], ALU.add, ['yc', 'tmpp'], ['yc'])
        k.tt('dve', ot[b][:], yc[:], gv[:], ALU.mult, ['yc', kgv], [f'ot{b}'])
        k.dma('pool', oc[rows, :], ot[b][:], r=[f'ot{b}'], final=True)

    gens = {}
    done = set()

    def adv(j):
        try:
            return next(gens[j])
        except StopIteration:
            done.add(j)
            return 'END'

    for step in range(NT + 2):
        if step < NT:
            gens[step] = tile(step)
            while adv(step) != 'STAGE':
                pass
        jb = step - 2
        if 0 <= jb < NT:
            n_g = 0
            while n_g < 2:
                if adv(jb) == 'GROUP':
                    n_g += 1
        ja = step - 1
        a_live = 0 <= ja < NT
        b_live = 0 <= jb < NT
        while a_live or b_live:
            if a_live:
                if adv(ja) == 'STAGE':
                    a_live = False
            if b_live:
                if adv(jb) == 'END':
                    b_live = False
    return k.finish()


def rwkv_consts(CH=64):
    c = -math.exp(-0.5)
    blk = np.kron(np.eye(128 // CH), np.ones((CH, CH)))
    s_idx = np.arange(128)[:, None]
    t_idx = np.arange(128)[None, :]
    triw = np.stack([c * blk * (s_idx <= t_idx), c * blk * (s_idx < t_idx), c * blk * (s_idx > t_idx)]).astype(np.float32)
    lt_, le_, gt_ = blk * (s_idx < t_idx), blk * (s_idx <= t_idx), blk * (t_idx < s_idx)
    mask5 = np.concatenate([lt_, gt_, lt_, le_, le_], 1).astype(np.float32)
    rowm = np.stack([(np.arange(128) < 64), (np.arange(128) >= 64)], 1).astype(np.float32) if CH == 64 else np.ones((128, 2), np.float32)
    return dict(ident=np.eye(128, dtype=np.float32), triw=triw, mask5=mask5, rowm=rowm)


def rwkv_host_inputs(s, p_rwkv, prm, NH=4, CH=64):
    L = p_rwkv.shape[0]
    cs = slice(64 * NH * s, 64 * NH * (s + 1))
    r_, w1, k_, v_, a1, g1 = np.split(p_rwkv, np.cumsum([512, 64, 512, 512, 64])[:5], axis=-1)
    mu = prm['rwkv_mu']
    mur, muw1, muk, muv, mua1, mug1 = np.split(mu, np.cumsum([512, 64, 512, 512, 64])[:5])
    zm = np.zeros(64, np.float32)
    mul = np.concatenate([muw1, zm, mua1, zm, mug1]).reshape(3, 128).T
    vecs = np.stack([prm['rwkv_w0'][cs], prm['rwkv_a0'][cs], prm['rwkv_k_k'][cs], prm['rwkv_k_a'][cs],
                     prm['rwkv_r_k'].reshape(-1)[cs], prm['rwkv_ln_gain'][cs], prm['rwkv_ln_bias'][cs]])
    c_ = np.ascontiguousarray
    d = dict(pr=c_(r_[:, cs]), pk=c_(k_[:, cs]), pv=c_(v_[:, cs]),
             mu1=c_(np.concatenate([mur[cs], muk[cs], muv[cs]])),
             plw=c_(w1.T), pla=c_(a1.T), plg=c_(g1.T), mul=c_(mul),
             w2=c_(prm['rwkv_w2'][:, cs]), a2=c_(prm['rwkv_a2'][:, cs]),
             g2=c_(prm['rwkv_g2'][:, cs]), vecs=c_(vecs))
    d.update(rwkv_consts(CH))
    return d


FM0 = [(0, 128, 0), (128, 128, 128), (256, 128, 256), (384, 128, 384), (1536, 16, 512)] + \
      [(1552 + j * 128, 128, 528 + j * 128) for j in range(4)]
NF0 = 1040
FM1 = [(512, 64, 0), (1600, 64, 64), (1664, 128, 128)] + [(1792 + j * 128, 128, 256 + j * 128) for j in range(8)]
NF1 = 1280


def host_params(inp):
    c_ = lambda a: np.ascontiguousarray(np.asarray(a), dtype=np.float32)
    P = {}
    P['ident'] = np.eye(128, dtype=np.float32)
    P['triu'] = np.triu(np.ones((128, 128), np.float32))
    P['trigt'] = np.tril(np.ones((128, 128), np.float32), -1)
    for l in range(2):
        for j in range(7):
            P[f'g{l}_{j}'] = c_(inp['norm_gain'][l][j])
        for nm in ('xa_wq', 'xa_wk', 'xa_wv', 'xa_wo', 'mlp_w1', 'mlp_w2'):
            P[f'{nm}{l}'] = c_(inp[nm][l])
    P['w_in0'] = c_(inp['ab_w_in'][0])
    P['w_in1'] = c_(inp['cd_w_in'][0])
    P['w_out0'] = c_(inp['ab_w_out'][0])
    P['w_out1'] = c_(inp['cd_w_out'][0])
    P['wglu'] = c_(inp['s5_w_glu'][0])
    P['bglu'] = c_(inp['s5_b_glu'][0])
    prm0 = {k_: np.asarray(inp[k_][0]) for k_ in inp if k_.startswith('s5_') or k_.startswith('gla_')}
    prm1 = {k_: np.asarray(inp[k_][0]) for k_ in inp if k_.startswith('rwkv_') or k_.startswith('lru_')}
    for s in range(2):
        cs = slice(s * 128, (s + 1) * 128)
        P[f'gla_w2_{s}'] = c_(prm0['gla_w_decay2'][:, cs])
        P[f'gla_bd_{s}'] = c_(prm0['gla_b_decay'][None, cs])
        P[f'gla_gn_{s}'] = c_(prm0['gla_norm_gain'][2 * s:2 * s + 2].reshape(256))
        d = s5_host_inputs(s, np.zeros((2, 512), np.float32), prm0)
        for nm in ('lam_re', 'lam_im', 'lstep', 'Bre', 'Bim', 'Cre', 'Cim', 'dsk'):
            P[f's5_{nm}_{s}'] = c_(d[nm])
        P['iota_p'] = c_(d['iota_p'])
        P['iota_f'] = c_(d['iota_f'])
        if s == 0:
            d = rwkv_host_inputs(0, np.zeros((2, 1792), np.float32), prm1, 8, 64)
            for nm in ('mu1', 'mul', 'w2', 'a2', 'g2', 'vecs'):
                P[f'rw_{nm}'] = c_(d[nm])
            for nm in ('triw', 'mask5', 'rowm'):
                P[f'rw_{nm}'] = c_(d[nm])
        d = lru_host_inputs(s, np.zeros((2, 512), np.float32), np.zeros((2, 512), np.float32), prm1)
        for nm in ('cw', 'cb', 'Wa', 'Wx', 'ba', 'bx', 'lam'):
            P[f'lru_{nm}_{s}'] = c_(d[nm])
    return P


def build_fused(P, L):
    k = K(fused=True)
    X = {nm: k.xin(nm, a.shape) for nm, a in P.items()}
    x = k.xin('x', [L, D])
    mem = k.xin('mem', [256, D])
    out = k.xout('out', [L, D])
    proj0 = k.scratch('proj0', [L, 2064])
    PT0 = k.scratch('PT0', [NF0, L])
    proj1 = k.scratch('proj1', [L, 2816])
    PT1 = k.scratch('PT1', [NF1, L])
    o = k.scratch('o', [L, D])
    odT = k.scratch('odT', [512, L])
    h1 = k.scratch('h1', [L, D])
    h2 = k.scratch('h2', [L, D])
    h3 = k.scratch('h3', [L, D])

    def cblock(l, hin, hout, glu, ob_fm):
        io = dict(oa=o[:, 0:512], hin=hin, wout=X[f'w_out{l}'], g1=X[f'g{l}_1'], ident=X['ident'], hout=h1)
        if ob_fm:
            io['obT'] = odT
        else:
            io['ob'] = o[:, 512:1024]
        if glu:
            io.update(wglu=X['wglu'], bglu=X['bglu'])
        k.begin_phase(f'C1_{l}', io)
        build_C1(L, glu, k=k, ob_fm=ob_fm)
        k.begin_phase(f'C2_{l}', dict(hin=h1, mem=mem, wq=X[f'xa_wq{l}'], wk=X[f'xa_wk{l}'], wv=X[f'xa_wv{l}'], wo=X[f'xa_wo{l}'],
                                      g2=X[f'g{l}_2'], g3=X[f'g{l}_3'], g6=X[f'g{l}_6'], ident=X['ident'], hout=h2))
        build_C2(L, k=k)
        k.begin_phase(f'C3_{l}', dict(hin=h2, w1=X[f'mlp_w1{l}'], w2=X[f'mlp_w2{l}'], g4=X[f'g{l}_4'], g5=X[f'g{l}_5'],
                                      ident=X['ident'], hout=hout))
        build_C3(L, k=k)

    k.begin_phase('A0', dict(x=x, gain=X['g0_0'], W=X['w_in0'], ident=X['ident'], out=proj0, outT=PT0))
    build_A2(L, 2064, FM0, NF0, k=k)
    for s in range(2):
        io_g = dict(qT=PT0[s * 128:(s + 1) * 128, :], kT=PT0[256 + s * 128:256 + (s + 1) * 128, :],
                    ktok=proj0[:, 256 + s * 128:256 + (s + 1) * 128], v=proj0[:, 512 + s * 256:512 + (s + 1) * 256],
                    gate=proj0[:, 1024 + s * 256:1024 + (s + 1) * 256], dlrT=PT0[512:528, :],
                    w2=X[f'gla_w2_{s}'], bdec=X[f'gla_bd_{s}'], gn=X[f'gla_gn_{s}'], triu=X['triu'],
                    trigt=X['trigt'], oa=o[:, s * 256:(s + 1) * 256])
        k.begin_phase(f'GLA{s}', io_g)
        build_GLA(L, k=k)
    for s in range(2):
        io_s = dict(uT=PT0[528 + s * 256:528 + (s + 1) * 256, :], u=proj0[:, 1552 + s * 256:1552 + (s + 1) * 256],
                    triu=X['triu'], iota_p=X['iota_p'], iota_f=X['iota_f'], y=o[:, 512 + s * 256:512 + (s + 1) * 256])
        for nm in ('lam_re', 'lam_im', 'lstep', 'Bre', 'Bim', 'Cre', 'Cim', 'dsk'):
            io_s[nm] = X[f's5_{nm}_{s}']
        k.begin_phase(f'S5{s}', io_s)
        build_S5(L, k=k)
    cblock(0, x, h3, True, False)
    k.begin_phase('A1', dict(x=h3, gain=X['g1_0'], W=X['w_in1'], ident=X['ident'], out=proj1, outT=PT1))
    build_A2(L, 2816, FM1, NF1, k=k)
    io = dict(pr=proj1[:, 0:512], pk=proj1[:, 576:1088], pv=proj1[:, 1088:1600], plw=PT1[0:64, :], pla=PT1[64:128, :],
              plg=PT1[128:256, :], ident=X['ident'], triw=X['rw_triw'], mask5=X['rw_mask5'], rowm=X['rw_rowm'], oc=o[:, 0:512])
    for nm in ('mu1', 'mul', 'w2', 'a2', 'g2', 'vecs'):
        io[nm] = X[f'rw_{nm}']
    k.begin_phase('RW', io)
    build_RWKVP(L, k=k, CH=64)
    streams = []
    for s in range(2):
        io = dict(xbT=PT1[256 + s * 256:256 + (s + 1) * 256, :], gateT=PT1[768 + s * 256:768 + (s + 1) * 256, :],
                  odT=odT[s * 256:(s + 1) * 256, :])
        for nm in ('cw', 'cb', 'Wa', 'Wx', 'ba', 'bx', 'lam'):
            io[nm] = X[f'lru_{nm}_{s}']
        streams.append((f'l{s}_', io, lambda kk: gen_LRU(L, kk)))
    k.begin_phase('LRU', {})
    run_streams(k, streams)
    k.finish()
    cblock(1, h3, out, False, True)
    return k.finish_program()


BATCH, SEQ = 4, 4096
_CACHE = {}


def kernel(**inp):
    inp = {k_: np.asarray(v_) for k_, v_ in inp.items()}
    P = host_params(inp)
    if 'nc' not in _CACHE:
        _CACHE['nc'] = build_fused(P, SEQ)
    nc = _CACHE['nc']
    maps = []
    for b in range(BATCH):
        m = dict(P)
        m['x'] = np.ascontiguousarray(inp['x'][b], dtype=np.float32)
        m['mem'] = np.ascontiguousarray(inp['mem'][b], dtype=np.float32)
        maps.append(m)
    res = run_bass_kernel_spmd(nc, maps, core_ids=list(range(BATCH))).results
    return np.ascontiguousarray(np.stack([res[b]['out'] for b in range(BATCH)]).astype(np.float32))
```

```python
import os
import math
from contextlib import ExitStack


import numpy as np
import concourse.bass as bass
import concourse.mybir as mybir
from concourse.bass_utils import run_bass_kernel_spmd

F32 = mybir.dt.float32
BF16 = mybir.dt.bfloat16
I32 = mybir.dt.int32
AF = mybir.ActivationFunctionType
ALU = mybir.AluOpType
AX = mybir.AxisListType

ENGS = ['pe', 'act', 'dve', 'pool', 'sp']
NDMA_SLOTS = 8
SAME_ENGINE_SYNC = os.environ.get("NOSELF", "0") != "1"


class Prog:
    def __init__(self, nc):
        self.nc = nc
        self.ops = {e: [] for e in ENGS}
        self.cnt = {e: 0 for e in ENGS}
        self.last_w = {}
        self.readers = {}
        self.seen = {e: {} for e in ENGS}
        self.dma_n = {e: 0 for e in ENGS}
        self.dma_tok = {e: [None] * NDMA_SLOTS for e in ENGS}
        self.final_tokens = []
        from contextlib import ExitStack
        self.sem_stack = ExitStack()
        self.sems = {}
        for e in ['pe', 'act', 'dve', 'pool']:
            self.sems[('c', e)] = self.sem_stack.enter_context(nc.semaphore("s_c_" + e))
        for q in ['sp', 'pool']:
            for sl in range(NDMA_SLOTS):
                self.sems[('d', q, sl)] = self.sem_stack.enter_context(nc.semaphore(f"s_d_{q}_{sl}"))

    def barrier(self):
        toks = []
        for e in ['pe', 'act', 'dve', 'pool']:
            if self.cnt[e] > 0:
                toks.append((('c', e), self.cnt[e]))
        for q in ENGS:
            for t in self.dma_tok[q]:
                if t is not None:
                    toks.append(t)
        for e in ENGS:
            waits = []
            for (sem, val) in toks:
                if sem == ('c', e):
                    continue
                if self.seen[e].get(sem, 0) >= val:
                    continue
                waits.append((sem, val))
                self.seen[e][sem] = val
            if waits:
                self.ops[e].append((waits, None, None))
        self.last_w = {}
        self.readers = {}

    def _deps(self, eng, reads, writes):
        toks = []
        for r in reads:
            t = self.last_w.get(r)
            if t is not None:
                toks.append(t)
        for w in writes:
            t = self.last_w.get(w)
            if t is not None:
                toks.append(t)
            toks.extend(self.readers.get(w, []))
        need = {}
        for (sem, val) in toks:
            if not SAME_ENGINE_SYNC and sem == ('c', eng):
                continue
            if sem == ('c', 'pe') and eng == 'pe':
                continue
            if self.seen[eng].get(sem, 0) >= val:
                continue
            if need.get(sem, 0) < val:
                need[sem] = val
        for sem, val in need.items():
            self.seen[eng][sem] = val
        return list(need.items())

    def _commit(self, tok, reads, writes):
        for w in writes:
            self.last_w[w] = tok
            self.readers[w] = []
        for r in reads:
            if r in writes:
                continue
            self.readers.setdefault(r, []).append(tok)

    def op(self, eng, fn, reads=(), writes=()):
        self.nrec = getattr(self, 'nrec', 0) + 1
        if self.nrec > int(os.environ.get("MAXOPS", "100000000")):
            return None
        kp = getattr(self, 'key_prefix', '')
        reads = [r if r.startswith('ps') else kp + r for r in reads]
        writes = [w if w.startswith('ps') else kp + w for w in writes]
        pk = getattr(self, 'ps_prefix', '')
        reads = [('ps' + pk + r[2:]) if r.startswith('ps') else r for r in reads]
        writes = [('ps' + pk + w[2:]) if w.startswith('ps') else w for w in writes]
        writes = list(writes) + [r for r in reads if r.startswith('ps') and r not in writes]
        waits = self._deps(eng, reads, writes)
        self.cnt[eng] += 1
        tok = (('c', eng), self.cnt[eng])
        self.ops[eng].append((waits, fn, tok))
        self._commit(tok, reads, writes)
        return tok

    def dma(self, q, out, in_, reads=(), writes=(), final=False, **kw):
        self.nrec = getattr(self, 'nrec', 0) + 1
        if self.nrec > int(os.environ.get("MAXOPS", "100000000")):
            return None
        kp = getattr(self, 'key_prefix', '')
        reads = [kp + r for r in reads]
        writes = [kp + w for w in writes]
        waits = self._deps(q, reads, writes)
        n = self.dma_n[q]
        slot = n % NDMA_SLOTS
        prev = self.dma_tok[q][slot]
        if prev is not None and self.seen[q].get(prev[0], 0) < prev[1]:
            waits.append(prev)
            self.seen[q][prev[0]] = prev[1]
        tok = (('d', q, slot), 16 * (n // NDMA_SLOTS + 1))
        self.dma_n[q] += 1
        self.dma_tok[q][slot] = tok

        def fn(e, out=out, in_=in_, kw=kw):
            return e.dma_start(out=out, in_=in_, **kw)
        self.ops[q].append((waits, fn, tok))
        self._commit(tok, reads, writes)
        if final:
            self.final_tokens.append(tok)
        return tok

    def emit(self, last=True):
        nc = self.nc
        sems = self.sems
        with nc.Block() as block:
            final = list(self.final_tokens) if last else []

            def run(e, name):
                for waits, fn, tok in self.ops[name]:
                    for (s, v) in waits:
                        e.wait_ge(sems[s], v)
                    if fn is None:
                        continue
                    inst = fn(e)
                    inc = 16 if tok[0][0] == 'd' else 1
                    inst.then_inc(sems[tok[0]], inc)
                if name == 'sp':
                    for (s, v) in final:
                        e.wait_ge(sems[s], v)
                self.ops[name] = []

            @block.tensor
            def _(e):
                run(e, 'pe')

            @block.scalar
            def _(e):
                run(e, 'act')

            @block.vector
            def _(e):
                run(e, 'dve')

            @block.gpsimd
            def _(e):
                run(e, 'pool')

            @block.sync
            def _(e):
                run(e, 'sp')
        if last:
            self.sem_stack.close()


D = 1024
KC = 8
EPS = 1e-6


class K:
    def __init__(self, fused=False):
        self.nc = bass.Bass("TRN2", target_bir_lowering=False)
        self.st = ExitStack()
        self.P = Prog(self.nc)
        self.n = 0
        self.fused = fused
        self.io = {}
        self.pfx = ""

    def begin_phase(self, name, io):
        self.pfx = name + "_"
        self.io = io
        self.st = ExitStack()
        for a in ('wstage', 'rr_cache', 'identf', 'identb'):
            if hasattr(self, a):
                delattr(self, a)

    def scratch(self, name, shape, dt=F32):
        return self.nc.dram_tensor(name, list(shape), dt, kind="Internal").ap()

    def xin(self, name, arr_shape, dt=F32):
        return self.nc.dram_tensor(name, list(arr_shape), dt, kind="ExternalInput").ap()

    def xout(self, name, arr_shape, dt=F32):
        return self.nc.dram_tensor(name, list(arr_shape), dt, kind="ExternalOutput").ap()

    def din(self, name, shape, dt=F32):
        if self.fused:
            ap = self.io[name]
            assert list(ap.shape) == list(shape), (name, ap.shape, shape)
            return ap
        return self.nc.dram_tensor(name, list(shape), dt, kind="ExternalInput").ap()

    def dout(self, name, shape, dt=F32):
        if self.fused:
            ap = self.io[name]
            assert list(ap.shape) == list(shape), (name, ap.shape, shape)
            return ap
        return self.nc.dram_tensor(name, list(shape), dt, kind="ExternalOutput").ap()

    def sb(self, name, shape, dt=F32):
        pers = getattr(self, 'persist', None)
        if pers is not None and (self.pfx + name) in pers:
            return pers[self.pfx + name]
        return self.st.enter_context(self.nc.sbuf_tensor(self.pfx + name, list(shape), dt))

    def push_scope(self, persistent):
        self.persist = getattr(self, 'persist', None) or {}
        for (name, shape, dt) in persistent:
            self.persist[self.pfx + name] = self.st.enter_context(self.nc.sbuf_tensor(self.pfx + name, list(shape), dt))
        self._st_saved = self.st
        self.st = ExitStack()

    def pop_scope(self):
        self.P.barrier()
        self.P.emit(last=False)
        self.st.close()
        self.st = self._st_saved

    def ps(self, name, shape, dt=F32):
        return self.st.enter_context(self.nc.psum_tensor(self.pfx + name, list(shape), dt))

    def finish(self, last=True):
        if self.fused:
            self.P.barrier()
            self.P.emit(last=False)
            self.st.close()
            return None
        self.P.emit()
        self.st.close()
        return self.nc

    def finish_program(self):
        self.P.emit(last=True)
        return self.nc

    def mm(self, out, lhsT, rhs, start, stop, r, w):
        self.P.op('pe', lambda e: e.matmul(out, lhsT=lhsT, rhs=rhs, start=start, stop=stop), reads=r, writes=w)

    def tr(self, out, in_, ident, r, w):
        self.P.op('pe', lambda e: e.transpose(out=out, in_=in_, identity=ident), reads=list(r) + ['ident'], writes=w)

    def act(self, out, in_, func, r, w, **kw):
        self.P.op('act', lambda e: e.activation(out=out, in_=in_, func=func, **kw), reads=r, writes=w)

    def tt(self, eng, out, in0, in1, op, r, w):
        self.P.op(eng, lambda e: e.tensor_tensor(out=out, in0=in0, in1=in1, op=op), reads=r, writes=w)

    def ts(self, eng, out, in0, s1, s2, op0, op1, r, w):
        if op1 is None:
            self.P.op(eng, lambda e: e.tensor_scalar(out=out, in0=in0, scalar1=s1, scalar2=None, op0=op0), reads=r, writes=w)
        else:
            self.P.op(eng, lambda e: e.tensor_scalar(out=out, in0=in0, scalar1=s1, scalar2=s2, op0=op0, op1=op1), reads=r, writes=w)

    def stt(self, out, in0, scalar, in1, op0, op1, r, w):
        self.P.op('dve', lambda e: e.scalar_tensor_tensor(out=out, in0=in0, scalar=scalar, in1=in1, op0=op0, op1=op1),
                  reads=r, writes=w)

    def cp(self, eng, out, in_, r, w):
        if eng == 'act':
            self.P.op('act', lambda e: e.copy(out=out, in_=in_), reads=r, writes=w)
        else:
            self.P.op(eng, lambda e: e.tensor_copy(out=out, in_=in_), reads=r, writes=w)

    def recip(self, out, in_, r, w):
        self.P.op('dve', lambda e: e.reciprocal(out=out, in_=in_), reads=r, writes=w)

    def memset(self, eng, ap, val, w):
        self.P.op(eng, lambda e: e.memset(ap, val), reads=[], writes=w)

    def dma(self, q, out, in_, r=(), w=(), final=False, **kw):
        self.P.dma(q, out, in_, reads=r, writes=w, final=final, **kw)

    def consts(self, ident_d):
        self.identf = self.sb("identf", [128, 128], F32)
        self.identb = self.sb("identb", [128, 128], BF16)
        self.dma('sp', self.identf[:], ident_d, w=['ident'])
        self.cp('dve', self.identb[:], self.identf[:], ['ident'], ['ident'])

    def gain_cols(self, name, g_d):
        t = self.sb(name, [128, KC], F32)
        self.dma('sp', t[:], g_d.rearrange("(kc p) -> p kc", p=128), w=[name], allow_slow_non_contiguous=True)
        return t

    def bcast_row(self, name, vec_d, n):
        t = self.sb(name, [128, n], F32)
        self.dma('sp', t[:], vec_d.partition_broadcast(128), w=[name])
        return t

    def load_weight(self, name, w_d, kchunks, ncols, gcol=None, gkey=None, stage_cols=2048, q='sp'):
        wb = self.sb(name, [128, kchunks, ncols], BF16)
        if not hasattr(self, 'wstage'):
            self.wstage = [self.sb(f"wstage{i}", [128, stage_cols], F32) for i in range(2)]
            self.wstage_n = 0
            self.wstage_cols = stage_cols
        sc = self.wstage_cols
        wv = w_d.rearrange("(kc p) n -> p kc n", p=128)
        for kc in range(kchunks):
            for c0 in range(0, ncols, sc):
                cw = min(sc, ncols - c0)
                b = self.wstage_n % 2
                self.wstage_n += 1
                stg = self.wstage[b]
                self.dma(q, stg[:, 0:cw], wv[:, kc, c0:c0 + cw], w=[f'wstage{b}'])
                eng = 'act' if (kc % 2 == 0) else 'dve'
                if gcol is not None:
                    if eng == 'act':
                        self.act(wb[:, kc, c0:c0 + cw], stg[:, 0:cw], AF.Copy, [f'wstage{b}', gkey], [f'{name}{kc}'],
                                 scale=gcol[:, kc:kc + 1])
                    else:
                        self.ts('dve', wb[:, kc, c0:c0 + cw], stg[:, 0:cw], gcol[:, kc:kc + 1], None, ALU.mult, None,
                                [f'wstage{b}', gkey], [f'{name}{kc}'])
                else:
                    self.cp(eng, wb[:, kc, c0:c0 + cw], stg[:, 0:cw], [f'wstage{b}'], [f'{name}{kc}'])
        return wb

    def rstd_of(self, x_ap, xkey, ss, rstd, junk, key, ncols=D):
        self.act(junk, x_ap, AF.Square, [xkey], ['junk', key + 'ss'], accum_out=ss)
        self.ts('dve', rstd, ss, 1.0 / ncols, EPS, ALU.mult, ALU.add, [key + 'ss'], [key])
        self.act(rstd, rstd, AF.Sqrt, [key], [key])
        self.recip(rstd, rstd, [key], [key])


def pipeline(make_gen, n):
    active = []
    for i in range(n):
        for g in list(active):
            try:
                next(g)
            except StopIteration:
                active.remove(g)
        g = make_gen(i)
        active.append(g)
        try:
            next(g)
        except StopIteration:
            active.remove(g)
    while active:
        for g in list(active):
            try:
                next(g)
            except StopIteration:
                active.remove(g)


def pipeline_gen(make_gen, n):
    active = []
    for i in range(n):
        for g in list(active):
            try:
                next(g)
            except StopIteration:
                active.remove(g)
        g = make_gen(i)
        active.append(g)
        try:
            next(g)
        except StopIteration:
            active.remove(g)
        yield
    while active:
        for g in list(active):
            try:
                next(g)
            except StopIteration:
                active.remove(g)
        yield


def run_streams(k, streams):
    base_pfx = k.pfx
    gens = []
    for (pf, io, gf) in streams:
        gens.append([pf, io, None, gf])
    active = list(gens)
    while active:
        for st in list(active):
            pf, io, g, gf = st
            k.pfx = base_pfx + pf
            k.P.key_prefix = pf
            k.P.ps_prefix = pf
            k.io = io
            try:
                if g is None:
                    st[2] = gf(k)
                    g = st[2]
                next(g)
            except StopIteration:
                active.remove(st)
    k.pfx = base_pfx
    k.P.key_prefix = ''
    k.P.ps_prefix = ''


GELU_C = 1.5957691216057308


def norm_T(k, xt, xkey, xn, xnkey, xT_dst, xTkey, psT, psTkey, ss, rstd, junk, key, evac_eng='act'):
    k.rstd_of(xt, xkey, ss, rstd, junk, key)
    k.ts('dve', xn, xt, rstd, None, ALU.mult, None, [xkey, key], [xnkey])
    for kc in range(KC):
        k.tr(psT[:, kc * 128:(kc + 1) * 128], xn[:, kc * 128:(kc + 1) * 128], k.identb[:], [xnkey], [psTkey])
    k.cp(evac_eng, xT_dst, psT[:].rearrange("p (k t) -> p k t", k=KC), [psTkey], [xTkey])


def post_norm_res(k, ps2, pskeys, ht, hkey, gbc, gkey, tmp2, tmpkeys, ss2, rstd, junk, key):
    for j in range(2):
        k.act(junk[:, 0:512], ps2[j], AF.Square, [pskeys[j]], ['junk', key + f'ss{j}'], accum_out=ss2[:, j:j + 1])
    k.tt('dve', ss2[:, 0:1], ss2[:, 0:1], ss2[:, 1:2], ALU.add, [key + 'ss0', key + 'ss1'], [key + 'ss0'])
    k.ts('dve', rstd, ss2[:, 0:1], 1.0 / D, EPS, ALU.mult, ALU.add, [key + 'ss0'], [key])
    k.act(rstd, rstd, AF.Sqrt, [key], [key])
    k.recip(rstd, rstd, [key], [key])
    for j in range(2):
        sl = slice(j * 512, (j + 1) * 512)
        k.stt(tmp2[j], ps2[j], rstd, gbc[:, sl], ALU.mult, ALU.mult, [pskeys[j], key, gkey], [tmpkeys[j]])
        k.tt('pool', ht[:, sl], ht[:, sl], tmp2[j], ALU.add, [tmpkeys[j], hkey], [hkey])


def build_C1(NTOK, glu, k=None, ob_fm=False):
    k = k or K()
    NT = NTOK // 128
    oa = k.din("oa", [NTOK, 512])
    if ob_fm:
        obT = k.din("obT", [512, NTOK])
    else:
        ob = k.din("ob", [NTOK, 512])
    hin = k.din("hin", [NTOK, D])
    wout = k.din("wout", [D, D])
    g1 = k.din("g1", [D])
    ident_d = k.din("ident", [128, 128])
    if glu:
        wglu = k.din("wglu", [512, 512])
        bglu = k.din("bglu", [512])
    hout = k.dout("hout", [NTOK, D])
    k.consts(ident_d)
    g1bc = k.bcast_row("g1bc", g1, D)
    Wout = k.load_weight("Wout", wout, KC, D, stage_cols=1024)
    if glu:
        Wglu = k.load_weight("Wglu", wglu, 4, 512)
        bgbc = k.bcast_row("bgbc", bglu, 512)

    def ring(nm, shape, n, dt=F32):
        return [k.sb(f"{nm}{j}", shape, dt) for j in range(n)]
    oc = ring("oc", [128, D], 10 if glu else 4)
    ocb = ring("ocb", [128, D], 3, BF16)
    oT = ring("oT", [128, KC, 128], 3, BF16)
    ht = ring("ht", [128, D], 4)
    mix = ring("mix", [128, D], 5)
    tmp = ring("tmp", [128, D], 3)
    ss2 = ring("ss2", [128, 2], 4)
    rstd = ring("rstd", [128, 1], 5)
    junk = k.sb("junk", [128, D], BF16)
    if ob_fm:
        obt = ring("obt", [128, 4, 128], 4)
    if glu:
        yb = ring("yb", [128, 512], 3, BF16)
        yT = ring("yT", [128, 4, 128], 3, BF16)
        t1 = ring("t1", [128, 512], 9)
        zs = ring("zs", [128, 512], 4)
        psTg = k.ps("psTg", [128, D], BF16)
        psG = k.ps("psG", [128, 512])
    psTm = [k.ps(f"psTm{j}", [128, D], BF16) for j in range(2)]
    psM = [k.ps(f"psM{j}", [128, 512]) for j in range(4)]

    def tile(i):
        rows = slice(i * 128, (i + 1) * 128)
        def T(lst, nm):
            j = i % len(lst)
            return lst[j], f'{nm}{j}'
        oc_, koc = T(oc, 'oc'); ocb_, kocb = T(ocb, 'ocb'); oT_, koT = T(oT, 'oT'); ht_, kht = T(ht, 'ht')
        mix_, kmix = T(mix, 'mix'); tmp_, ktmp = T(tmp, 'tmp'); ss_, kss = T(ss2, 'ss2'); rs_, krs = T(rstd, 'rstd')
        pm = [psM[2 * (i % 2)], psM[2 * (i % 2) + 1]]
        kpm = [f'psM{2 * (i % 2)}', f'psM{2 * (i % 2) + 1}']
        ptm, kptm = psTm[i % 2], f'psTm{i % 2}'
        kA, kB = koc + 'A', koc + 'B'
        k.dma('sp', oc_[:, 0:512], oa[rows, :], w=[kA])
        if ob_fm:
            obt_, kobt = T(obt, 'obt')
            k.dma('sp', obt_[:], obT[:, rows].rearrange("(a p) t -> p a t", p=128), w=[kobt])
        else:
            k.dma('sp', oc_[:, 512:1024], ob[rows, :], w=[kB])
        yield
        if glu:
            y = oc_[:, 512:1024]
            yb_, kyb = T(yb, 'yb'); yT_, kyT = T(yT, 'yT'); t1_, kt1 = T(t1, 't1'); zs_, kzs = T(zs, 'zs')
            k.cp('dve', yb_[:], y, [kB], [kyb])
            k.act(t1_[:], y, AF.Square, [kB], [kt1])
            k.act(t1_[:], t1_[:], AF.Copy, [kt1], [kt1], scale=0.044715, bias=1.0)
            yield
            for kc in range(4):
                k.tr(psTg[:, kc * 128:(kc + 1) * 128], yb_[:, kc * 128:(kc + 1) * 128], k.identb[:], [kyb], ['psTg'])
            k.tt('pool', t1_[:], t1_[:], y, ALU.mult, [kt1, kB], [kt1])
            yield
            k.cp('act', yT_[:], psTg[:, 0:512].rearrange("p (k t) -> p k t", k=4), ['psTg'], [kyT])
            k.act(t1_[:], t1_[:], AF.Sigmoid, [kt1], [kt1], scale=GELU_C)
            yield
            for kc in range(4):
                k.mm(psG[:], yT_[:, kc, :], Wglu[:, kc, :], kc == 0, kc == 3, [kyT, f'Wglu{kc}'], ['psG'])
            yield
            k.tt('dve', zs_[:], psG[:], bgbc[:], ALU.add, ['psG', 'bgbc'], [kzs])
            yield
            k.act(zs_[:], zs_[:], AF.Sigmoid, [kzs], [kzs])
            yield
            k.tt('dve', zs_[:], t1_[:], zs_[:], ALU.mult, [kt1, kzs], [kzs])
            k.tt('dve', y, y, zs_[:], ALU.mult, [kB, kzs], [kB])
        if ob_fm:
            k.cp('dve', ocb_[:, 0:512], oc_[:, 0:512], [kA], [kocb])
            k.cp('pool', oT_[:, 4:8, :], obt_[:], [kobt], [koT + 'b'])
        else:
            k.cp('dve', ocb_[:], oc_[:], [kA, kB], [kocb])
        yield
        nk = 4 if ob_fm else KC
        for kc in range(nk):
            k.tr(ptm[:, kc * 128:(kc + 1) * 128], ocb_[:, kc * 128:(kc + 1) * 128], k.identb[:], [kocb], [kptm])
        yield
        k.cp('act', oT_[:, 0:nk, :], ptm[:, 0:nk * 128].rearrange("p (k t) -> p k t", k=nk), [kptm], [koT])
        yield
        for cg in range(2):
            for kc in range(KC):
                ok_ = (koT + 'b') if (ob_fm and kc >= 4) else koT
                k.mm(pm[cg][:], oT_[:, kc, :], Wout[:, kc, cg * 512:(cg + 1) * 512], kc == 0, kc == KC - 1,
                     [ok_, f'Wout{kc}'], [kpm[cg]])
        yield
        for j in range(2):
            k.act(junk[:, 0:512], pm[j][:], AF.Square, [kpm[j]], ['junk', kss], accum_out=ss_[:, j:j + 1])
        for j in range(2):
            k.cp('act', mix_[:, j * 512:(j + 1) * 512], pm[j][:], [kpm[j]], [kmix])
        k.dma('sp', ht_[:], hin[rows, :], w=[kht])
        yield
        k.tt('dve', ss_[:, 0:1], ss_[:, 0:1], ss_[:, 1:2], ALU.add, [kss], [kss])
        k.ts('dve', rs_[:], ss_[:, 0:1], 1.0 / D, EPS, ALU.mult, ALU.add, [kss], [krs])
        yield
        k.act(rs_[:], rs_[:], AF.Sqrt, [krs], [krs])
        yield
        k.recip(rs_[:], rs_[:], [krs], [krs])
        k.stt(tmp_[:], mix_[:], rs_[:], g1bc[:], ALU.mult, ALU.mult, [kmix, krs, 'g1bc'], [ktmp])
        yield
        k.tt('pool', ht_[:], ht_[:], tmp_[:], ALU.add, [kht, ktmp], [kht])
        k.dma('pool', hout[rows, :], ht_[:], r=[kht], final=True)

    pipeline(tile, NT)
    return k.finish()


def build_C3(NTOK, k=None):
    k = k or K()
    NB = NTOK // 512
    DFF = 4096
    FC = DFF // 128
    hin = k.din("hin", [NTOK, D])
    w1 = k.din("w1", [D, DFF])
    w2 = k.din("w2", [DFF, D])
    g4 = k.din("g4", [D])
    g5 = k.din("g5", [D])
    ident_d = k.din("ident", [128, 128])
    hout = k.dout("hout", [NTOK, D])
    k.consts(ident_d)
    g4c = k.gain_cols("g4c", g4)
    g5bc = k.bcast_row("g5bc", g5, D)
    W1 = k.load_weight("W1", w1, KC, DFF, gcol=g4c, gkey='g4c', stage_cols=512)
    W2 = k.load_weight("W2", w2, FC, D, stage_cols=512)
    ht = [k.sb(f"ht{i}", [128, D]) for i in range(4)]
    xn = [k.sb(f"xn{i}", [128, D], BF16) for i in range(2)]
    xT = k.sb("xT", [128, KC, 512], BF16)
    AT = k.sb("AT", [128, FC, 512], BF16)
    sq = [k.sb(f"sq{i}", [128, 512]) for i in range(2)]
    junk = k.sb("junk", [128, D], BF16)
    ss = [k.sb(f"ss{i}", [128, 1]) for i in range(2)]
    ss2 = [k.sb(f"ss2{i}", [128, 2]) for i in range(2)]
    rstd = [k.sb(f"rstd{i}", [128, 1]) for i in range(2)]
    rstd2 = [k.sb(f"rstdb{i}", [128, 1]) for i in range(2)]
    ss4 = k.sb("ss4", [128, 4])
    rs4 = k.sb("rs4", [128, 4])
    psT = k.ps("psT", [128, D], BF16)
    psU = [k.ps(f"psU{i}", [128, 512]) for i in range(3)]
    psD = [k.ps(f"psD{i}", [128, 512]) for i in range(4)]
    nu = 0
    for blk in range(NB):
        for tt in range(4):
            i = blk * 4 + tt
            k.dma('sp', ht[tt][:], hin[i * 128:(i + 1) * 128, :], w=[f'ht{tt}'])
        for tt in range(4):
            k.act(junk[:], ht[tt][:], AF.Square, [f'ht{tt}'], ['junk', f'nss{tt}'], accum_out=ss4[:, tt:tt + 1])
        k.ts('dve', rs4[:], ss4[:], 1.0 / D, EPS, ALU.mult, ALU.add, [f'nss{t_}' for t_ in range(4)], ['rs4'])
        k.act(rs4[:], rs4[:], AF.Sqrt, ['rs4'], ['rs4'])
        k.recip(rs4[:], rs4[:], ['rs4'], ['rs4'])
        for tt in range(4):
            b = tt % 2
            k.ts('dve', xn[b][:], ht[tt][:], rs4[:, tt:tt + 1], None, ALU.mult, None, [f'ht{tt}', 'rs4'], [f'xn{b}'])
            for kc in range(KC):
                k.tr(psT[:, kc * 128:(kc + 1) * 128], xn[b][:, kc * 128:(kc + 1) * 128], k.identb[:], [f'xn{b}'], ['psT'])
            k.cp('act', xT[:, :, tt * 128:(tt + 1) * 128], psT[:].rearrange("p (k t) -> p k t", k=KC), ['psT'], ['xT'])
        for fc in range(FC):
            pu = nu % 3
            nu += 1
            for kc in range(KC):
                k.mm(psU[pu][:], W1[:, kc, fc * 128:(fc + 1) * 128], xT[:, kc, :], kc == 0, kc == KC - 1,
                     [f'W1{kc}', 'xT'], [f'psU{pu}'])
            sb_ = fc % 2
            k.act(sq[sb_][:], psU[pu][:], AF.Square, [f'psU{pu}'], [f'sq{sb_}'])
            k.stt(AT[:, fc, :], psU[pu][:], 0.0, sq[sb_][:], ALU.is_gt, ALU.mult, [f'psU{pu}', f'sq{sb_}'], ['AT'])
        for tt in range(4):
            i = blk * 4 + tt
            b = i % 2
            rows = slice(i * 128, (i + 1) * 128)
            for cg in range(2):
                pd = 2 * b + cg
                for fc in range(FC):
                    k.mm(psD[pd][:], AT[:, fc, tt * 128:(tt + 1) * 128], W2[:, fc, cg * 512:(cg + 1) * 512],
                         fc == 0, fc == FC - 1, ['AT', f'W2{fc}'], [f'psD{pd}'])
            post_norm_res(k, [psD[2 * b][:], psD[2 * b + 1][:]], [f'psD{2 * b}', f'psD{2 * b + 1}'], ht[tt], f'ht{tt}',
                          g5bc, 'g5bc', [sq[0][:], sq[1][:]], ['sq0', 'sq1'], ss2[b], rstd2[b][:], junk, f'pn{b}')
            k.dma('pool', hout[rows, :], ht[tt][:], r=[f'ht{tt}'], final=True)
    return k.finish()


def build_C2(NTOK, k=None):
    k = k or K()
    NB = NTOK // 512
    MEM = 256
    hin = k.din("hin", [NTOK, D])
    mem = k.din("mem", [MEM, D])
    wq = k.din("wq", [D, D])
    wk = k.din("wk", [D, D])
    wv = k.din("wv", [D, D])
    wo = k.din("wo", [D, D])
    g2 = k.din("g2", [D])
    g3 = k.din("g3", [D])
    g6 = k.din("g6", [D])
    ident_d = k.din("ident", [128, 128])
    hout = k.dout("hout", [NTOK, D])
    k.consts(ident_d)
    g2c = k.gain_cols("g2c", g2)
    g6c = k.gain_cols("g6c", g6)
    g3bc = k.bcast_row("g3bc", g3, D)
    Wk = k.load_weight("Wk", wk, KC, D, gcol=g6c, gkey='g6c', stage_cols=1024)
    Wv = k.load_weight("Wv", wv, KC, D, gcol=g6c, gkey='g6c', stage_cols=1024)
    Wq = k.load_weight("Wq", wq, KC, D, gcol=g2c, gkey='g2c', stage_cols=1024)
    Wo = k.load_weight("Wo", wo, KC, D, stage_cols=1024)
    ht = [k.sb(f"ht{i}", [128, D]) for i in range(2)]
    xn = [k.sb(f"xn{i}", [128, D], BF16) for i in range(2)]
    xT = [k.sb(f"xT{i}", [128, KC, 512], BF16) for i in range(2)]
    memT = k.sb("memT", [128, KC, MEM], BF16)
    KT = k.sb("KT", [128, KC, MEM], BF16)
    V = k.sb("V", [128, 2, D], BF16)
    QT = [k.sb(f"QT{i}", [128, KC, 512], BF16) for i in range(2)]
    Pm = [k.sb(f"Pm{i}", [128, 4, MEM], BF16) for i in range(3)]
    Pn = [k.sb(f"Pn{i}", [128, 4, MEM], BF16) for i in range(3)]
    PT = [k.sb(f"PT{i}", [128, 8, 128], BF16) for i in range(3)]
    OT = [k.sb(f"OT{i}", [128, KC, 128], BF16) for i in range(3)]
    tmp = [k.sb(f"tmp{i}", [128, 512]) for i in range(2)]
    junk = k.sb("junk", [128, D], BF16)
    ss = [k.sb(f"ss{i}", [128, 1]) for i in range(2)]
    ss2 = [k.sb(f"ss2{i}", [128, 2]) for i in range(2)]
    rstd = [k.sb(f"rstd{i}", [128, 1]) for i in range(2)]
    rstd2 = [k.sb(f"rstdb{i}", [128, 1]) for i in range(2)]
    mx = [k.sb(f"mx{i}", [128, 4]) for i in range(3)]
    sm = [k.sb(f"sm{i}", [128, 4]) for i in range(3)]
    psT = k.ps("psT", [128, D], BF16)
    psA = k.ps("psA", [128, 1024])
    psS = k.ps("psS", [128, 1024])
    psX = k.ps("psX", [128, 1024])
    for mt in range(2):
        k.dma('sp', ht[mt][:], mem[mt * 128:(mt + 1) * 128, :], w=[f'ht{mt}'])
        norm_T(k, ht[mt][:], f'ht{mt}', xn[mt][:], f'xn{mt}', memT[:, :, mt * 128:(mt + 1) * 128], 'memT', psT[:], 'psT',
               ss[mt][:], rstd[mt][:], junk[:], f'n{mt}')
    for cc in range(KC):
        pa = cc % 2
        for kc in range(KC):
            k.mm(psA[:, pa * 512:pa * 512 + MEM], Wk[:, kc, cc * 128:(cc + 1) * 128], memT[:, kc, :], kc == 0, kc == KC - 1,
                 [f'Wk{kc}', 'memT'], [f'psA{pa}'])
        k.cp('act' if cc % 2 else 'dve', KT[:, cc, :], psA[:, pa * 512:pa * 512 + MEM], [f'psA{pa}'], [f'KT{cc}'])
    for mt in range(2):
        for cg in range(2):
            for kc in range(KC):
                k.mm(psX[:, cg * 512:(cg + 1) * 512], memT[:, kc, mt * 128:(mt + 1) * 128], Wv[:, kc, cg * 512:(cg + 1) * 512],
                     kc == 0, kc == KC - 1, ['memT', f'Wv{kc}'], [f'psX{cg}'])
            k.cp('act' if cg else 'dve', V[:, mt, cg * 512:(cg + 1) * 512], psX[:, cg * 512:(cg + 1) * 512], [f'psX{cg}'], [f'V{mt}{cg}'])
    xt6 = [k.sb(f"xt6_{i}", [128, D]) for i in range(6)]
    ss6 = [k.sb(f"ss6_{i}", [128, 1]) for i in range(4)]
    rs6 = [k.sb(f"rs6_{i}", [128, 1]) for i in range(5)]
    xn3 = [k.sb(f"xn3_{i}", [128, D], BF16) for i in range(3)]
    psTx = k.ps("psTx", [128, D], BF16)

    def tile(i):
        blk, tt = divmod(i, 4)
        xb = blk % 2
        b = i % 3
        rows = slice(i * 128, (i + 1) * 128)
        tsl = slice(tt * 128, (tt + 1) * 128)
        def T(lst, nm):
            j = i % len(lst)
            return lst[j], f'{nm}{j}'
        xt_, kxt = T(xt6, 'xt6'); ss_, kss = T(ss6, 'ss6'); rs_, krs = T(rs6, 'rs6'); xn_, kxn = T(xn3, 'xn3')
        hb = i % 2
        k.dma('sp', xt_[:], hin[rows, :], w=[kxt])
        yield
        k.act(junk[:], xt_[:], AF.Square, [kxt], ['junk', kss], accum_out=ss_[:])
        yield
        k.ts('dve', rs_[:], ss_[:], 1.0 / D, EPS, ALU.mult, ALU.add, [kss], [krs])
        yield
        k.act(rs_[:], rs_[:], AF.Sqrt, [krs], [krs])
        yield
        k.recip(rs_[:], rs_[:], [krs], [krs])
        k.ts('dve', xn_[:], xt_[:], rs_[:], None, ALU.mult, None, [kxt, krs], [kxn])
        yield
        for kc in range(KC):
            k.tr(psTx[:, kc * 128:(kc + 1) * 128], xn_[:, kc * 128:(kc + 1) * 128], k.identb[:], [kxn], ['psTx'])
        yield
        k.cp('act', xT[xb][:, :, tsl], psTx[:].rearrange("p (k t) -> p k t", k=KC), ['psTx'], [f'xT{xb}'])
        yield
        if tt == 3:
            for cc in range(KC):
                pa = cc % 2
                for kc in range(KC):
                    k.mm(psA[:, pa * 512:(pa + 1) * 512], Wq[:, kc, cc * 128:(cc + 1) * 128], xT[xb][:, kc, :], kc == 0, kc == KC - 1,
                         [f'Wq{kc}', f'xT{xb}'], [f'psA{pa}'])
                k.cp('act' if cc % 2 else 'dve', QT[xb][:, cc, :], psA[:, pa * 512:(pa + 1) * 512], [f'psA{pa}'], [f'QT{xb}{cc}'])
        yield
        yield
        yield
        yield
        for h in range(4):
            sb_ = h // 2
            for j in range(2):
                cc = 2 * h + j
                k.mm(psS[:, h * MEM:(h + 1) * MEM], QT[xb][:, cc, tsl], KT[:, cc, :], j == 0, j == 1,
                     [f'QT{xb}{cc}', f'KT{cc}'], [f'psS{sb_}'])
        k.P.op('dve', lambda e, b=b: e.tensor_reduce(out=mx[b][:], in_=psS[:].rearrange("p (h m) -> p h m", h=4),
                                                    axis=AX.X, op=ALU.max),
               reads=['psS0', 'psS1'], writes=[f'mx{b}'])
        k.ts('dve', mx[b][:], mx[b][:], -1.0 / 16.0, None, ALU.mult, None, [f'mx{b}'], [f'mx{b}'])
        for h in range(4):
            k.act(Pm[b][:, h, :], psS[:, h * MEM:(h + 1) * MEM], AF.Exp, [f'psS{h // 2}', f'mx{b}'], [f'Pm{b}', f'sm{b}'],
                  scale=1.0 / 16.0, bias=mx[b][:, h:h + 1], accum_out=sm[b][:, h:h + 1])
        k.recip(sm[b][:], sm[b][:], [f'sm{b}'], [f'sm{b}'])
        k.tt('dve', Pn[b][:], Pm[b][:], sm[b][:].unsqueeze(2).broadcast_to([128, 4, MEM]), ALU.mult,
             [f'Pm{b}', f'sm{b}'], [f'Pn{b}'])
        yield
        for h in range(4):
            for mt in range(2):
                k.tr(psT[:, (h * 2 + mt) * 128:(h * 2 + mt + 1) * 128], Pn[b][:, h, mt * 128:(mt + 1) * 128], k.identb[:],
                     [f'Pn{b}'], ['psT'])
        k.cp('act', PT[b][:], psT[:].rearrange("p (k t) -> p k t", k=8), ['psT'], [f'PT{b}'])
        for cc in range(KC):
            h = cc // 2
            pa = cc // 4
            for mt in range(2):
                k.mm(psA[:, cc * 128:(cc + 1) * 128], V[:, mt, cc * 128:(cc + 1) * 128], PT[b][:, h * 2 + mt, :],
                     mt == 0, mt == 1, [f'V{mt}{cc // 4}', f'PT{b}'], [f'psA{pa}'])
        k.cp('dve', OT[b][:, 0:4, :], psA[:, 0:512].rearrange("p (k t) -> p k t", k=4), ['psA0'], [f'OT{b}_0'])
        k.cp('act', OT[b][:, 4:8, :], psA[:, 512:1024].rearrange("p (k t) -> p k t", k=4), ['psA1'], [f'OT{b}_1'])
        k.dma('sp', ht[hb][:], hin[rows, :], w=[f'ht{hb}'])
        yield
        for cg in range(2):
            for cc in range(KC):
                k.mm(psX[:, cg * 512:(cg + 1) * 512], OT[b][:, cc, :], Wo[:, cc, cg * 512:(cg + 1) * 512],
                     cc == 0, cc == KC - 1, [f'OT{b}_{cc // 4}', f'Wo{cc}'], [f'psX{cg}'])
        post_norm_res(k, [psX[:, 0:512], psX[:, 512:1024]], ['psX0', 'psX1'], ht[hb], f'ht{hb}',
                      g3bc, 'g3bc', [tmp[0][:], tmp[1][:]], ['tmp0', 'tmp1'], ss2[b % 2], rstd2[b % 2][:], junk, f'pn{b % 2}')
        k.dma('pool', hout[rows, :], ht[hb][:], r=[f'ht{hb}'], final=True)

    pipeline(tile, NTOK // 128)
    return k.finish()


def build_A2(NTOK, NC, fm, NF, k=None):
    k = k or K()
    NB = NTOK // 512
    x = k.din("x", [NTOK, D])
    gain = k.din("gain", [D])
    W = k.din("W", [D, NC])
    ident_d = k.din("ident", [128, 128])
    out = k.dout("out", [NTOK, NC])
    outT = k.dout("outT", [NF, NTOK])
    k.consts(ident_d)
    gc = k.gain_cols("gc", gain)
    Wb = k.load_weight("Wb", W, KC, NC, gcol=gc, gkey='gc', stage_cols=1408)
    cgs = [(c0, min(512, NC - c0)) for c0 in range(0, NC, 512)]
    def ring(nm, shape, n, dt=F32):
        return [k.sb(f"{nm}{j}", shape, dt) for j in range(n)]
    xt = ring("xt", [128, D], 6)
    xn = ring("xn", [128, D], 3, BF16)
    xT = [k.sb(f"xT{i}", [128, KC, 512], BF16) for i in range(2)]
    ot = [k.sb(f"ot{i}", [128, NC]) for i in range(2)]
    ft = [k.sb(f"ft{i}", [128, 512]) for i in range(2)]
    junk = k.sb("junk", [128, D], BF16)
    ss = ring("ss", [128, 1], 4)
    rstd = ring("rstd", [128, 1], 5)
    psT = k.ps("psT", [128, D], BF16)
    psO = [k.ps(f"psO{i}", [128, 512]) for i in range(4)]
    psF = [k.ps(f"psF{i}", [128, 512]) for i in range(2)]
    cnt = {'no': 0, 'nf': 0}

    def tile(i):
        blk, tt = divmod(i, 4)
        xb = blk % 2
        def T(lst, nm):
            j = i % len(lst)
            return lst[j], f'{nm}{j}'
        xt_, kxt = T(xt, 'xt'); xn_, kxn = T(xn, 'xn'); ss_, kss = T(ss, 'ss'); rs_, krs = T(rstd, 'rstd')
        k.dma('sp', xt_[:], x[i * 128:(i + 1) * 128, :], w=[kxt])
        yield
        k.act(junk[:], xt_[:], AF.Square, [kxt], ['junk', kss], accum_out=ss_[:])
        yield
        k.ts('dve', rs_[:], ss_[:], 1.0 / D, EPS, ALU.mult, ALU.add, [kss], [krs])
        yield
        k.act(rs_[:], rs_[:], AF.Sqrt, [krs], [krs])
        yield
        k.recip(rs_[:], rs_[:], [krs], [krs])
        k.ts('dve', xn_[:], xt_[:], rs_[:], None, ALU.mult, None, [kxt, krs], [kxn])
        yield
        for kc in range(KC):
            k.tr(psT[:, kc * 128:(kc + 1) * 128], xn_[:, kc * 128:(kc + 1) * 128], k.identb[:], [kxn], ['psT'])
        yield
        k.cp('act', xT[xb][:, :, tt * 128:(tt + 1) * 128], psT[:].rearrange("p (k t) -> p k t", k=KC), ['psT'], [f'xT{xb}'])
        yield
        if tt != 3:
            return
        for t2 in range(4):
            i2 = blk * 4 + t2
            b = i2 % 2
            for ci, (c0, cw) in enumerate(cgs):
                pb = cnt['no'] % 4
                cnt['no'] += 1
                for kc in range(KC):
                    k.mm(psO[pb][:, 0:cw], xT[xb][:, kc, t2 * 128:(t2 + 1) * 128], Wb[:, kc, c0:c0 + cw], kc == 0, kc == KC - 1,
                         [f'xT{xb}', f'Wb{kc}'], [f'psO{pb}'])
                k.cp('dve' if pb % 2 == 0 else 'act', ot[b][:, c0:c0 + cw], psO[pb][:, 0:cw], [f'psO{pb}'], [f'ot{b}_{pb % 2}'])
            k.dma('pool', out[i2 * 128:(i2 + 1) * 128, :], ot[b][:], r=[f'ot{b}_0', f'ot{b}_1'], final=True)
        for (c0, cw, r0) in fm:
            pf = cnt['nf'] % 2
            cnt['nf'] += 1
            for kc in range(KC):
                k.mm(psF[pf][0:cw, :], Wb[:, kc, c0:c0 + cw], xT[xb][:, kc, :], kc == 0, kc == KC - 1,
                     [f'Wb{kc}', f'xT{xb}'], [f'psF{pf}'])
            k.cp('dve' if pf == 0 else 'act', ft[pf][0:cw, :], psF[pf][0:cw, :], [f'psF{pf}'], [f'ft{pf}'])
            k.dma('pool', outT[r0:r0 + cw, blk * 512:(blk + 1) * 512], ft[pf][0:cw, :], r=[f'ft{pf}'], final=True)

    pipeline(tile, NTOK // 128)
    return k.finish()


def gen_GLA(L, k):
    NT = L // 128
    qT = k.din("qT", [128, L])
    kT = k.din("kT", [128, L])
    ktok = k.din("ktok", [L, 128])
    v = k.din("v", [L, 256])
    gate = k.din("gate", [L, 256])
    dlrT = k.din("dlrT", [16, L])
    w2 = k.din("w2", [16, 128])
    bdec = k.din("bdec", [1, 128])
    gn = k.din("gn", [256])
    triu_d = k.din("triu", [128, 128])
    trigt_d = k.din("trigt", [128, 128])
    oa = k.dout("oa", [L, 256])

    triu = k.sb("triu_s", [128, 128])
    trigt = k.sb("trigt_s", [128, 128])
    k.dma('sp', triu[:], triu_d, w=['triu'])
    k.dma('sp', trigt[:], trigt_d, w=['trigt'])
    w2s = k.sb("w2s", [16, 128])
    k.dma('sp', w2s[:], w2, w=['w2s'])
    bds = k.sb("bds", [1, 128])
    k.dma('sp', bds[:], bdec, w=['bds'])
    ones1 = k.sb("ones1", [1, 128])
    k.memset('dve', ones1[:], 1.0, ['ones1'])
    gnbc = k.bcast_row("gnbc", gn, 256)
    S = k.sb("S", [128, 128], mybir.dt.float32r)
    zS = k.sb("zS", [128, 128])
    k.memset('dve', zS[:], 0.0, ['zS'])
    k.cp('dve', S[:], zS[:], ['zS'], ['S'])
    rm = k.sb("rm", [128, 2])
    k.memset('dve', rm[:], 0.0, ['rm'])
    k.memset('dve', rm[0:64, 0:1], 0.125, ['rm'])
    k.memset('dve', rm[64:128, 1:2], 0.125, ['rm'])

    def ring(nm, shape, n, dt=F32):
        return [k.sb(f"{nm}{j}", shape, dt) for j in range(n)]
    FR_ = mybir.dt.float32r
    triur = k.sb("triur", [128, 128], FR_)
    trigtr = k.sb("trigtr", [128, 128], FR_)
    k.cp('dve', triur[:], triu[:], ['triu'], ['triur'])
    k.cp('dve', trigtr[:], trigt[:], ['trigt'], ['trigtr'])
    vr = ring("vr", [128, 256], 10, FR_)
    qTt, kTt, kt, gt = ring("qTt", [128, 128], 8), ring("kTt", [128, 128], 8), ring("kt", [128, 128], 8), ring("gt", [128, 256], 8)
    vt = ring("vt", [128, 256], 11)
    dt_ = ring("dt", [16, 128], 3)
    la = ring("la", [128, 128], 4, mybir.dt.float32r)
    sg = ring("sg", [128, 256], 16)
    EqT, EkT, Eks = ring("EqT", [128, 128], 7), ring("EkT", [128, 128], 3), ring("Eks", [128, 128], 3)
    qin, kin, kst = ring("qin", [128, 2, 128], 5, mybir.dt.float32r), ring("kin", [128, 128], 3, mybir.dt.float32r), ring("kst", [128, 128], 5, mybir.dt.float32r)
    sc0, sc1 = ring("sc0_", [128, 128], 3, mybir.dt.float32r), ring("sc1_", [128, 128], 3, mybir.dt.float32r)
    osr = ring("osr", [128, 256], 6)
    osb = ring("osb", [128, 256], 3)
    ss, rs = ring("ss", [128, 2], 4), ring("rs", [128, 2], 5)
    ot = ring("ot", [128, 256], 3)
    junk = k.sb("junk", [128, 128])
    psZ = [k.ps(f"psZ{j}", [128, 512]) for j in range(2)]
    psA = [k.ps(f"psA{j}", [128, 512]) for j in range(2)]
    psB = [k.ps(f"psB{j}", [128, 512]) for j in range(2)]
    psC = [k.ps(f"psC{j}", [128, 512]) for j in range(2)]

    def tile(i):
        rows = slice(i * 128, (i + 1) * 128)
        R = lambda lst: (lst[i % len(lst)], f'{lst[0].name if hasattr(lst[0], "name") else id(lst)}_{i % len(lst)}')
        def T(lst, nm):
            j = i % len(lst)
            return lst[j], f'{nm}{j}'
        q_, kq = T(qTt, 'qTt'); kT_, kkT = T(kTt, 'kTt'); kt_, kkt = T(kt, 'kt'); v_, kv = T(vt, 'vt'); g_, kg = T(gt, 'gt')
        d_, kd = T(dt_, 'dt'); la_, kla = T(la, 'la'); sg_, ksg = T(sg, 'sg')
        Eq, kEq = T(EqT, 'EqT'); Ek, kEk = T(EkT, 'EkT'); Es, kEs = T(Eks, 'Eks')
        qi, kqi = T(qin, 'qin'); ki, kki = T(kin, 'kin'); ks, kks = T(kst, 'kst')
        scs = [T(sc0, 'sc0_'), T(sc1, 'sc1_')]
        orw, korw = T(osr, 'osr'); ob_, kob = T(osb, 'osb'); ss_, kss = T(ss, 'ss'); rs_, krs = T(rs, 'rs'); ot_, kot = T(ot, 'ot')
        pz, kpz = psZ[i % 2], f'psZ{i % 2}'
        pa, kpa = psA[i % 2], f'psA{i % 2}'
        pb, kpb = psB[i % 2], f'psB{i % 2}'
        pc, kpc = psC[i % 2], f'psC{i % 2}'
        k.dma('sp', q_[:], qT[:, rows], w=[kq])
        k.dma('sp', kT_[:], kT[:, rows], w=[kkT])
        k.dma('sp', kt_[:], ktok[rows, :], w=[kkt])
        k.dma('sp', v_[:], v[rows, :], w=[kv])
        k.dma('sp', g_[:], gate[rows, :], w=[kg])
        k.dma('sp', d_[:], dlrT[:, rows], w=[kd])
        yield
        k.mm(pz[:, 0:128], d_[:], w2s[:], True, False, [kd, 'w2s'], [kpz])
        k.mm(pz[:, 0:128], ones1[:], bds[:], False, True, ['ones1', 'bds'], [kpz])
        yield
        k.act(la_[:], pz[:, 0:128], AF.Exp, [kpz], [kla], scale=-1.0)
        k.act(la_[:], la_[:].bitcast(F32), AF.Ln, [kla], [kla], bias=1.0)
        k.act(sg_[:], g_[:], AF.Exp, [kg], [ksg], scale=-1.0)
        vr_, kvr = T(vr, 'vr')
        k.cp('act', vr_[:], v_[:], [kv], [kvr])
        yield
        k.ts('dve', la_[:], la_[:].bitcast(F32), -1.0 / 16.0, None, ALU.mult, None, [kla], [kla])
        k.ts('dve', sg_[:], sg_[:], 1.0, None, ALU.add, None, [ksg], [ksg])
        k.recip(sg_[:], sg_[:], [ksg], [ksg])
        yield
        k.mm(pa[:, 0:128], la_[:], triur[:], True, True, [kla, 'triur'], [kpa])
        k.mm(pa[:, 128:256], trigtr[:], la_[:], True, True, [kla, 'trigtr'], [kpa])
        yield
        k.act(Eq[:], pa[:, 0:128], AF.Exp, [kpa], [kEq])
        k.act(Ek[:], pa[:, 0:128], AF.Exp, [kpa], [kEk], scale=-1.0)
        k.act(Es[:], pa[:, 128:256], AF.Exp, [kpa], [kEs])
        yield
        for h in range(2):
            k.stt(qi[:, h, :], q_[:], rm[:, h:h + 1], Eq[:], ALU.mult, ALU.mult, [kq, kEq, 'rm'], [kqi])
        k.tt('pool', ki[:], kT_[:], Ek[:], ALU.mult, [kkT, kEk], [kki])
        k.tt('pool', ks[:], kt_[:], Es[:], ALU.mult, [kkt, kEs], [kks])
        k.tt('pool', sg_[:], sg_[:], g_[:], ALU.mult, [ksg, kg], [ksg])
        yield
        for h in range(2):
            hp = slice(h * 64, (h + 1) * 64)
            k.mm(pb[:, h * 128:(h + 1) * 128], ki[:], qi[:, h, :], True, True, [kki, kqi], [kpb])
        yield
        for h in range(2):
            k.tt('dve', scs[h][0][:], pb[:, h * 128:(h + 1) * 128], triu[:], ALU.mult, [kpb, 'triu'], [scs[h][1]])
        yield
        for h in range(2):
            hp = slice(h * 64, (h + 1) * 64)
            k.mm(pc[:, h * 128:(h + 1) * 128], scs[h][0][:], vr_[:, h * 128:(h + 1) * 128], True, False, [scs[h][1], kvr], [kpc])
            k.mm(pc[:, h * 128:(h + 1) * 128], qi[:, h, :], S[:], False, True, [kqi, 'S'], [kpc])
        k.mm(pc[:, 256:512], ks[:], vr_[:], True, True, [kks, kvr], [kpc])
        yield
        for h in range(2):
            hp = slice(h * 64, (h + 1) * 64)
            k.stt(S[hp, :], S[hp, :].bitcast(F32), Eq[hp, 127:128], pc[hp, 256 + h * 128:256 + (h + 1) * 128], ALU.mult, ALU.add,
                  ['S', kEq, kpc], ['S'])
        k.cp('act', orw[:], pc[:, 0:256], [kpc], [korw])
        yield
        for h in range(2):
            k.act(junk[:], orw[:, h * 128:(h + 1) * 128], AF.Square, [korw], ['junk', kss], accum_out=ss_[:, h:h + 1])
        yield
        k.ts('dve', rs_[:], ss_[:], 1.0 / 128.0, EPS, ALU.mult, ALU.add, [kss], [krs])
        yield
        k.act(rs_[:], rs_[:], AF.Ln, [krs], [krs])
        k.act(rs_[:], rs_[:], AF.Exp, [krs], [krs], scale=-0.5)
        yield
        for h in range(2):
            hs = slice(h * 128, (h + 1) * 128)
            k.stt(ob_[:, hs], orw[:, hs], rs_[:, h:h + 1], gnbc[:, hs], ALU.mult, ALU.mult, [korw, krs, 'gnbc'], [kob])
        yield
        k.tt('pool', ot_[:], ob_[:], sg_[:], ALU.mult, [kob, ksg], [kot])
        k.dma('pool', oa[rows, :], ot_[:], r=[kot], final=True)

    yield from pipeline_gen(tile, NT)


def build_GLA(L, k=None):
    k = k or K()
    for _ in gen_GLA(L, k):
        pass
    return k.finish()


TWO_PI = 2.0 * math.pi
C1 = 6.28125
C2 = TWO_PI - 6.28125
PI_LO = 3.1415925


def range_sincos(k, x, xkey, shape, s_out, c_out, skey, ckey, pfx):
    if not hasattr(k, 'rr_cache'):
        k.rr_cache = {}
    if pfx not in k.rr_cache:
        k.rr_cache[pfx] = (k.sb(pfx + "kf", shape), k.sb(pfx + "ki", shape, I32), k.sb(pfx + "r", shape), k.sb(pfx + "m", shape))
    kf, ki, r, m = k.rr_cache[pfx]
    a = lambda t: t[:]
    K1, K2, K3, K4 = pfx + 'kf', pfx + 'ki', pfx + 'r', pfx + 'm'
    k.ts('dve', a(kf), x, 1.0 / TWO_PI, None, ALU.mult, None, [xkey], [K1])
    k.cp('dve', a(ki), a(kf), [K1], [K2])
    k.cp('dve', a(kf), a(ki), [K2], [K1])
    k.stt(a(r), a(kf), -C1, x, ALU.mult, ALU.add, [K1, xkey], [K3])
    k.stt(a(r), a(kf), -C2, a(r), ALU.mult, ALU.add, [K1, K3], [K3])
    k.ts('dve', a(m), a(r), math.pi, -TWO_PI, ALU.is_gt, ALU.mult, [K3], [K4])
    k.tt('dve', a(r), a(r), a(m), ALU.add, [K3, K4], [K3])
    k.ts('dve', a(m), a(r), -math.pi, TWO_PI, ALU.is_lt, ALU.mult, [K3], [K4])
    k.tt('dve', a(r), a(r), a(m), ALU.add, [K3, K4], [K3])
    k.ts('dve', a(kf), a(r), PI_LO, -PI_LO, ALU.min, ALU.max, [K3], [K1])
    k.act(s_out, a(kf), AF.Sin, [K1], [skey])
    k.ts('dve', a(r), a(r), math.pi / 2, None, ALU.add, None, [K3], [K3])
    k.ts('dve', a(m), a(r), math.pi, -TWO_PI, ALU.is_gt, ALU.mult, [K3], [K4])
    k.tt('dve', a(r), a(r), a(m), ALU.add, [K3, K4], [K3])
    k.ts('dve', a(kf), a(r), PI_LO, -PI_LO, ALU.min, ALU.max, [K3], [K1])
    k.act(c_out, a(kf), AF.Sin, [K1], [ckey])


def gen_S5(L, k):
    NT = L // 128
    NS = 1024
    uT = k.din("uT", [256, L])
    u = k.din("u", [L, 256])
    lam_re = k.din("lam_re", [NS])
    lam_im = k.din("lam_im", [NS])
    lstep = k.din("lstep", [NS])
    Bre = k.din("Bre", [2, 128, 512])
    Bim = k.din("Bim", [2, 128, 512])
    Cre = k.din("Cre", [8, 128, 32])
    Cim = k.din("Cim", [8, 128, 32])
    dsk = k.din("dsk", [256])
    triu_d = k.din("triu", [128, 128])
    iop_d = k.din("iota_p", [128, 1])
    iof_d = k.din("iota_f", [128, 128])
    y = k.dout("y", [L, 256])

    k.push_scope([("triu_s", [128, 128], F32), ("dbc", [128, 256], F32), ("BBr", [128, 2, 512], mybir.dt.float32r), ("BBi", [128, 2, 512], mybir.dt.float32r),
                  ("Pr", [128, NS], F32), ("Pi", [128, NS], F32), ("Qr", [128, 8, 128], F32), ("Qi", [128, 8, 128], F32),
                  ("L128r", [128, 8], F32), ("L128i", [128, 8], F32), ("Cr", [128, 8, 32], F32), ("nCi", [128, 8, 32], F32),
                  ("car_r", [128, 8], F32), ("car_i", [128, 8], F32), ("ntriu", [128, 128], mybir.dt.float32r), ("nCr", [128, 8, 32], mybir.dt.float32r), ("triur", [128, 128], mybir.dt.float32r), ("Crr", [128, 8, 32], mybir.dt.float32r), ("nCir", [128, 8, 32], mybir.dt.float32r)])
    triu = k.sb("triu_s", [128, 128])
    k.dma('sp', triu[:], triu_d, w=['triu'])
    iop = k.sb("iop", [128, 1])
    k.dma('sp', iop[:], iop_d, w=['iop'])
    negp = k.sb("negp", [128, 1])
    k.ts('dve', negp[:], iop[:], -1.0, None, ALU.mult, None, ['iop'], ['negp'])
    iof = k.sb("iof", [128, 128])
    k.dma('sp', iof[:], iof_d, w=['iof'])
    dbc = k.bcast_row("dbc", dsk, 256)
    R = [128, NS]
    lr = k.bcast_row("lr", lam_re, NS)
    li = k.bcast_row("li", lam_im, NS)
    dl = k.bcast_row("dl", lstep, NS)
    k.ts('dve', lr[:], lr[:], -1e-4, None, ALU.min, None, ['lr'], ['lr'])
    k.act(dl[:], dl[:], AF.Exp, ['dl'], ['dl'])
    a_ = k.sb("a_", R)
    th = k.sb("th", R)
    k.tt('dve', a_[:], lr[:], dl[:], ALU.mult, ['lr', 'dl'], ['a_'])
    k.tt('dve', th[:], li[:], dl[:], ALU.mult, ['li', 'dl'], ['th'])
    sn = k.sb("sn", R)
    cs = k.sb("cs", R)
    range_sincos(k, th[:], 'th', R, sn[:], cs[:], 'sn', 'cs', 'rr_')
    ea = k.sb("ea", R)
    k.act(ea[:], a_[:], AF.Exp, ['a_'], ['ea'])
    nr = k.sb("nr", R)
    ni = k.sb("ni", R)
    k.tt('dve', nr[:], ea[:], cs[:], ALU.mult, ['ea', 'cs'], ['nr'])
    k.ts('dve', nr[:], nr[:], -1.0, None, ALU.add, None, ['nr'], ['nr'])
    k.tt('dve', ni[:], ea[:], sn[:], ALU.mult, ['ea', 'sn'], ['ni'])
    den = k.sb("den", R)
    t0 = k.sb("t0", R)
    k.tt('dve', den[:], lr[:], lr[:], ALU.mult, ['lr'], ['den'])
    k.tt('dve', t0[:], li[:], li[:], ALU.mult, ['li'], ['t0'])
    k.tt('dve', den[:], den[:], t0[:], ALU.add, ['den', 't0'], ['den'])
    k.recip(den[:], den[:], ['den'], ['den'])
    gr = k.sb("gr", R)
    gi = k.sb("gi", R)
    k.tt('dve', gr[:], nr[:], lr[:], ALU.mult, ['nr', 'lr'], ['gr'])
    k.tt('dve', t0[:], ni[:], li[:], ALU.mult, ['ni', 'li'], ['t0'])
    k.tt('dve', gr[:], gr[:], t0[:], ALU.add, ['gr', 't0'], ['gr'])
    k.tt('dve', gr[:], gr[:], den[:], ALU.mult, ['gr', 'den'], ['gr'])
    k.tt('dve', gi[:], ni[:], lr[:], ALU.mult, ['ni', 'lr'], ['gi'])
    k.tt('dve', t0[:], nr[:], li[:], ALU.mult, ['nr', 'li'], ['t0'])
    k.tt('dve', gi[:], gi[:], t0[:], ALU.subtract, ['gi', 't0'], ['gi'])
    k.tt('dve', gi[:], gi[:], den[:], ALU.mult, ['gi', 'den'], ['gi'])
    Br = k.sb("Br", [128, 2, 512])
    Bi = k.sb("Bi", [128, 2, 512])
    BBr = k.sb("BBr", [128, 2, 512])
    BBi = k.sb("BBi", [128, 2, 512])
    for hc in range(2):
        k.dma('sp', Br[:, hc, :], Bre[hc], w=[f'Br{hc}'])
        k.dma('sp', Bi[:, hc, :], Bim[hc], w=[f'Bi{hc}'])
    grv = gr[:].rearrange("p (h n) -> p h n", h=2)
    giv = gi[:].rearrange("p (h n) -> p h n", h=2)
    t0v = t0[:].rearrange("p (h n) -> p h n", h=2)
    BK = ['Br0', 'Br1', 'Bi0', 'Bi1']
    k.tt('dve', BBr[:], grv, Br[:], ALU.mult, ['gr'] + BK, ['BBr'])
    k.tt('dve', t0v, giv, Bi[:], ALU.mult, ['gi'] + BK, ['t0'])
    k.tt('dve', BBr[:], BBr[:].bitcast(F32), t0v, ALU.subtract, ['BBr', 't0'], ['BBr'])
    k.tt('dve', BBi[:], grv, Bi[:], ALU.mult, ['gr'] + BK, ['BBi'])
    k.tt('dve', t0v, giv, Br[:], ALU.mult, ['gi'] + BK, ['t0'])
    k.tt('dve', BBi[:], BBi[:].bitcast(F32), t0v, ALU.add, ['BBi', 't0'], ['BBi'])
    ang = k.sb("ang", R)
    k.ts('dve', ang[:], th[:], iop[:, 0:1], None, ALU.mult, None, ['th', 'iop'], ['ang'])
    Pr = k.sb("Pr", R)
    Pi = k.sb("Pi", R)
    range_sincos(k, ang[:], 'ang', R, sn[:], cs[:], 'sn', 'cs', 'rr_')
    k.act(ea[:], a_[:], AF.Exp, ['a_', 'negp'], ['ea'], scale=negp[:, 0:1])
    k.tt('dve', Pr[:], ea[:], cs[:], ALU.mult, ['ea', 'cs'], ['Pr'])
    k.stt(Pi[:], ea[:], -1.0, sn[:], ALU.mult, ALU.mult, ['ea', 'sn'], ['Pi'])
    Cs = [128, 8]
    lrc = k.sb("lrc", Cs)
    lic = k.sb("lic", Cs)
    dlc = k.sb("dlc", Cs)
    cv = lambda d: d.rearrange("(blk p) -> p blk", p=128)
    k.dma('sp', lrc[:], cv(lam_re), w=['lrc'], allow_slow_non_contiguous=True)
    k.dma('sp', lic[:], cv(lam_im), w=['lic'], allow_slow_non_contiguous=True)
    k.dma('sp', dlc[:], cv(lstep), w=['dlc'], allow_slow_non_contiguous=True)
    k.ts('dve', lrc[:], lrc[:], -1e-4, None, ALU.min, None, ['lrc'], ['lrc'])
    k.act(dlc[:], dlc[:], AF.Exp, ['dlc'], ['dlc'])
    ac = k.sb("ac", Cs)
    thc = k.sb("thc", Cs)
    k.tt('dve', ac[:], lrc[:], dlc[:], ALU.mult, ['lrc', 'dlc'], ['ac'])
    k.tt('dve', thc[:], lic[:], dlc[:], ALU.mult, ['lic', 'dlc'], ['thc'])
    Qr = k.sb("Qr", [128, 8, 128])
    Qi = k.sb("Qi", [128, 8, 128])
    angv = ang[:].rearrange("p (b t) -> p b t", b=8)
    eav = ea[:].rearrange("p (b t) -> p b t", b=8)
    for blk in range(8):
        k.ts('dve', angv[:, blk, :], iof[:], thc[:, blk:blk + 1], None, ALU.mult, None, ['iof', 'thc'], ['ang'])
    range_sincos(k, ang[:], 'ang', R, sn[:], cs[:], 'sn', 'cs', 'rr_')
    for blk in range(8):
        k.act(eav[:, blk, :], iof[:], AF.Exp, ['iof', 'ac'], ['ea'], scale=ac[:, blk:blk + 1])
    k.tt('dve', Qr[:].rearrange("p b t -> p (b t)"), ea[:], cs[:], ALU.mult, ['ea', 'cs'], ['Qr'])
    k.tt('dve', Qi[:].rearrange("p b t -> p (b t)"), ea[:], sn[:], ALU.mult, ['ea', 'sn'], ['Qi'])
    a128 = k.sb("a128", Cs)
    s128 = k.sb("s128", Cs)
    c128 = k.sb("c128", Cs)
    L128r = k.sb("L128r", Cs)
    L128i = k.sb("L128i", Cs)
    k.ts('dve', a128[:], thc[:], 128.0, None, ALU.mult, None, ['thc'], ['a128'])
    range_sincos(k, a128[:], 'a128', Cs, s128[:], c128[:], 's128', 'c128', 'rc_')
    k.act(a128[:], ac[:], AF.Exp, ['ac', 's128', 'c128'], ['a128'], scale=128.0)
    k.tt('dve', L128r[:], a128[:], c128[:], ALU.mult, ['a128', 'c128'], ['L128r'])
    k.tt('dve', L128i[:], a128[:], s128[:], ALU.mult, ['a128', 's128'], ['L128i'])
    Cr = k.sb("Cr", [128, 8, 32])
    nCi = k.sb("nCi", [128, 8, 32])
    k.dma('sp', Cr[:], Cre.rearrange("b p c -> p b c"), w=['Cr'])
    k.dma('sp', nCi[:], Cim.rearrange("b p c -> p b c"), w=['nCi'])
    k.ts('dve', nCi[:], nCi[:], -1.0, None, ALU.mult, None, ['nCi'], ['nCi'])
    car_r = k.sb("car_r", Cs)
    car_i = k.sb("car_i", Cs)
    k.memset('dve', car_r[:], 0.0, ['car_r0', 'car_r1'])
    k.memset('dve', car_i[:], 0.0, ['car_i0', 'car_i1'])
    ntriu = k.sb("ntriu", [128, 128])
    k.ts('dve', ntriu[:], triu[:], -1.0, None, ALU.mult, None, ['triu'], ['ntriu'])
    nCr = k.sb("nCr", [128, 8, 32])
    k.ts('dve', nCr[:], Cr[:], -1.0, None, ALU.mult, None, ['Cr'], ['nCr'])
    triur = k.sb("triur", [128, 128])
    k.cp('dve', triur[:], triu[:], ['triu'], ['triur'])
    Crr = k.sb("Crr", [128, 8, 32])
    k.cp('dve', Crr[:], Cr[:], ['Cr'], ['Crr'])
    nCir = k.sb("nCir", [128, 8, 32])
    k.cp('dve', nCir[:], nCi[:], ['nCi'], ['nCir'])
    k.pop_scope()
    if hasattr(k, 'rr_cache'):
        del k.rr_cache
    def ring(nm, shape, n, dt=F32):
        return [k.sb(f"{nm}{j}", shape, dt) for j in range(n)]
    FR_ = mybir.dt.float32r
    uTt = ring("uTt", [128, 128], 3)
    uTr = ring("uTr", [128, 128], 3, FR_)
    ut = ring("ut", [128, 128], 5)
    yo = ring("yo", [128, 128], 9)
    m1, m2, m3, m4 = ring("m1_", [128, 512], 3, FR_), ring("m2_", [128, 512], 3, FR_), ring("m3_", [128, 512], 3, FR_), ring("m4_", [128, 512], 3, FR_)
    Xtr, Xti = ring("Xtr", [128, 512], 3), ring("Xti", [128, 512], 3)
    Gr, Gi = ring("Gr", [128, 4, 128], 3), ring("Gi", [128, 4, 128], 3)
    n1, n2, n3, n4 = ring("n1_", [128, 512], 3, FR_), ring("n2_", [128, 512], 3, FR_), ring("n3_", [128, 512], 3, FR_), ring("n4_", [128, 512], 3, FR_)
    Hr, Hi = ring("Hr", [128, 4, 128], 3), ring("Hi", [128, 4, 128], 3)
    cc1 = [k.sb(f"cc1_{h}", [128, 4]) for h in range(2)]
    cc2 = [k.sb(f"cc2_{h}", [128, 4]) for h in range(2)]
    psXr = k.ps("psXr", [128, 512])
    psXi = k.ps("psXi", [128, 512])
    psGr = k.ps("psGr", [128, 512])
    psGi = k.ps("psGi", [128, 512])
    psY = k.ps("psY", [128, 512])
    fl = lambda t: t[:].rearrange("p b t -> p (b t)")

    def item(j):
        i, hc = divmod(j, 2)
        rows = slice(i * 128, (i + 1) * 128)
        cs_ = slice(hc * 512, (hc + 1) * 512)
        bs = slice(hc * 4, (hc + 1) * 4)
        def T(lst, nm):
            q = j % len(lst)
            return lst[q], f'{nm}{q}'
        uT_, kuT = T(uTt, 'uTt'); uR_, kuR = T(uTr, 'uTr'); ut_, kut = T(ut, 'ut'); yo_, kyo = T(yo, 'yo')
        m1_, km1 = T(m1, 'm1'); m2_, km2 = T(m2, 'm2'); m3_, km3 = T(m3, 'm3'); m4_, km4 = T(m4, 'm4')
        Xr_, kXr = T(Xtr, 'Xtr'); Xi_, kXi = T(Xti, 'Xti'); Gr_, kGr = T(Gr, 'Gr'); Gi_, kGi = T(Gi, 'Gi')
        n1_, kn1 = T(n1, 'n1'); n2_, kn2 = T(n2, 'n2'); n3_, kn3 = T(n3, 'n3'); n4_, kn4 = T(n4, 'n4')
        Hr_, kHr = T(Hr, 'Hr'); Hi_, kHi = T(Hi, 'Hi')
        k.dma('sp', uT_[:], uT[hc * 128:(hc + 1) * 128, rows], w=[kuT])
        k.dma('sp', ut_[:], u[rows, hc * 128:(hc + 1) * 128], w=[kut])
        yield
        k.cp('act', uR_[:], uT_[:], [kuT], [kuR])
        yield
        k.mm(psXr[:], uR_[:], BBr[:, hc, :], True, True, [kuR, 'BBr'], ['psXr'])
        k.mm(psXi[:], uR_[:], BBi[:, hc, :], True, True, [kuR, 'BBi'], ['psXi'])
        yield
        k.tt('dve', m1_[:], psXr[:], Pr[:, cs_], ALU.mult, ['psXr', 'Pr'], [km1])
        k.tt('dve', m3_[:], psXr[:], Pi[:, cs_], ALU.mult, ['psXr', 'Pi'], [km3])
        k.tt('dve', m2_[:], psXi[:], Pi[:, cs_], ALU.mult, ['psXi', 'Pi'], [km2])
        k.tt('dve', m4_[:], psXi[:], Pr[:, cs_], ALU.mult, ['psXi', 'Pr'], [km4])
        yield
        k.tt('pool', yo_[:], ut_[:], dbc[:, hc * 128:(hc + 1) * 128], ALU.mult, [kut, 'dbc'], [kyo])
        yield
        for nb in range(4):
            ns = slice(nb * 128, (nb + 1) * 128)
            k.mm(psGr[:, ns], m1_[:, ns], triur[:], True, False, [km1, 'triur'], ['psGr'])
            k.mm(psGr[:, ns], m2_[:, ns], ntriu[:], False, True, [km2, 'ntriu'], ['psGr'])
            k.mm(psGi[:, ns], m3_[:, ns], triur[:], True, False, [km3, 'triur'], ['psGi'])
            k.mm(psGi[:, ns], m4_[:, ns], triur[:], False, True, [km4, 'triur'], ['psGi'])
        yield
        k.tt('dve', Gr_[:], psGr[:].rearrange("p (b t) -> p b t", b=4),
             car_r[:, bs].unsqueeze(2).broadcast_to([128, 4, 128]), ALU.add, ['psGr', f'car_r{hc}'], [kGr])
        k.tt('dve', Gi_[:], psGi[:].rearrange("p (b t) -> p b t", b=4),
             car_i[:, bs].unsqueeze(2).broadcast_to([128, 4, 128]), ALU.add, ['psGi', f'car_i{hc}'], [kGi])
        gr127 = Gr_[:, :, 127]
        gi127 = Gi_[:, :, 127]
        CK = [f'cc1{hc}', f'cc2{hc}']
        k.tt('dve', cc1[hc][:], L128r[:, bs], gr127, ALU.mult, ['L128r', kGr], [CK[0]])
        k.tt('dve', cc2[hc][:], L128i[:, bs], gi127, ALU.mult, ['L128i', kGi], [CK[1]])
        k.tt('dve', car_r[:, bs], cc1[hc][:], cc2[hc][:], ALU.subtract, CK, [f'car_r{hc}'])
        k.tt('dve', cc1[hc][:], L128r[:, bs], gi127, ALU.mult, ['L128r', kGi], [CK[0]])
        k.tt('dve', cc2[hc][:], L128i[:, bs], gr127, ALU.mult, ['L128i', kGr], [CK[1]])
        k.tt('dve', car_i[:, bs], cc1[hc][:], cc2[hc][:], ALU.add, CK, [f'car_i{hc}'])
        yield
        qr = Qr[:, bs, :].rearrange("p b t -> p (b t)")
        qi = Qi[:, bs, :].rearrange("p b t -> p (b t)")
        k.tt('dve', n1_[:], fl(Gr_), qr, ALU.mult, [kGr, 'Qr'], [kn1])
        k.tt('dve', n2_[:], fl(Gi_), qi, ALU.mult, [kGi, 'Qi'], [kn2])
        k.tt('dve', n3_[:], fl(Gi_), qr, ALU.mult, [kGi, 'Qr'], [kn3])
        k.tt('dve', n4_[:], fl(Gr_), qi, ALU.mult, [kGr, 'Qi'], [kn4])
        yield
        for nb in range(4):
            blk = hc * 4 + nb
            ns = slice(nb * 128, (nb + 1) * 128)
            yo_s = psY[:, blk * 32:(blk + 1) * 32]
            k.mm(yo_s, n1_[:, ns], Crr[:, blk, :], True, False, [kn1, 'Crr'], ['psY'])
            k.mm(yo_s, n2_[:, ns], nCr[:, blk, :], False, False, [kn2, 'nCr'], ['psY'])
            k.mm(yo_s, n3_[:, ns], nCir[:, blk, :], False, False, [kn3, 'nCir'], ['psY'])
            k.mm(yo_s, n4_[:, ns], nCir[:, blk, :], False, True, [kn4, 'nCir'], ['psY'])
        yield
        k.tt('dve', yo_[:], yo_[:], psY[:, hc * 128:(hc + 1) * 128], ALU.add, [kyo, 'psY'], [kyo])
        yield
        k.dma('pool', y[rows, hc * 128:(hc + 1) * 128], yo_[:], r=[kyo], final=True)

    yield from pipeline_gen(item, 2 * NT)


def build_S5(L, k=None):
    k = k or K()
    for _ in gen_S5(L, k):
        pass
    return k.finish()


def s5_host_inputs(s, proj_u, prm):
    gs = slice(16 * s, 16 * s + 16)
    cs = slice(256 * s, 256 * s + 256)
    uc = np.ascontiguousarray(proj_u[:, cs])
    Bre = np.zeros((2, 128, 512), np.float32)
    Bim = np.zeros((2, 128, 512), np.float32)
    Cre = np.zeros((8, 128, 32), np.float32)
    Cim = np.zeros((8, 128, 32), np.float32)
    b_re, b_im = prm['s5_b_re'][gs], prm['s5_b_im'][gs]
    c_re, c_im = prm['s5_c_re'][gs], prm['s5_c_im'][gs]
    for g in range(16):
        hc, gl = g // 8, g % 8
        Bre[hc, gl * 16:(gl + 1) * 16, gl * 64:(gl + 1) * 64] = b_re[g].T
        Bim[hc, gl * 16:(gl + 1) * 16, gl * 64:(gl + 1) * 64] = b_im[g].T
        blk, g2 = g // 2, g % 2
        Cre[blk, g2 * 64:(g2 + 1) * 64, g2 * 16:(g2 + 1) * 16] = c_re[g].T
        Cim[blk, g2 * 64:(g2 + 1) * 64, g2 * 16:(g2 + 1) * 16] = c_im[g].T
    return dict(uT=np.ascontiguousarray(uc.T), u=uc,
                lam_re=np.ascontiguousarray(prm['s5_lambda_re'][gs].reshape(-1)),
                lam_im=np.ascontiguousarray(prm['s5_lambda_im'][gs].reshape(-1)),
                lstep=np.ascontiguousarray(np.repeat(prm['s5_log_step'][gs], 64)),
                Bre=Bre, Bim=Bim, Cre=Cre, Cim=Cim, dsk=np.ascontiguousarray(prm['s5_d'][cs]),
                triu=np.triu(np.ones((128, 128), np.float32)),
                iota_p=np.arange(128, dtype=np.float32).reshape(128, 1),
                iota_f=np.tile(np.arange(128, dtype=np.float32)[None], (128, 1)))


GELU_C = 1.5957691216057308


def gen_LRU(L, k):
    TT = 512
    NCH = L // TT
    xbT = k.din("xbT", [256, L])
    gateT = k.din("gateT", [256, L])
    cw_d = k.din("cw", [128, 2, 4])
    cb_d = k.din("cb", [128, 2])
    Wa_d = k.din("Wa", [2, 128, 128])
    Wx_d = k.din("Wx", [2, 128, 128])
    ba_d = k.din("ba", [128, 2])
    bx_d = k.din("bx", [128, 2])
    lam_d = k.din("lam", [128, 2])
    odT = k.dout("odT", [256, L])
    cw = k.sb("cw_s", [128, 2, 4])
    cb = k.sb("cb_s", [128, 2])
    Wa = k.sb("Wa_s", [128, 2, 128])
    Wx = k.sb("Wx_s", [128, 2, 128])
    ba = k.sb("ba_s", [128, 2])
    bx = k.sb("bx_s", [128, 2])
    c8 = k.sb("c8", [128, 2])
    k.dma('sp', cw[:], cw_d, w=['cw'])
    k.dma('sp', cb[:], cb_d, w=['cb'])
    k.dma('sp', Wa[:], Wa_d.rearrange("b p n -> p b n"), w=['Wa'])
    k.dma('sp', Wx[:], Wx_d.rearrange("b p n -> p b n"), w=['Wx'])
    k.dma('sp', ba[:], ba_d, w=['ba'])
    k.dma('sp', bx[:], bx_d, w=['bx'])
    k.dma('sp', c8[:], lam_d, w=['c8'])
    k.act(c8[:], c8[:], AF.Exp, ['c8'], ['c8'], scale=-1.0)
    k.act(c8[:], c8[:], AF.Ln, ['c8'], ['c8'], bias=1.0)
    k.ts('dve', c8[:], c8[:], -8.0, None, ALU.mult, None, ['c8'], ['c8'])
    hlast = k.sb("hlast", [128, 2])
    k.memset('dve', hlast[:], 0.0, ['hlast0', 'hlast1'])

    def ring(nm, shape, n):
        return [k.sb(f"{nm}{j}", shape) for j in range(n)]
    xh = ring("xh", [128, TT + 3], 3)
    gt = ring("gt", [128, TT], 8)
    xc = ring("xc", [128, TT], 5)
    r, ig, a, a2 = ring("r", [128, TT], 2), ring("ig", [128, TT], 3), ring("a", [128, TT], 5), ring("a2", [128, TT], 3)
    bt = ring("bt", [128, TT], 4)
    g2 = ring("g2", [128, TT], 5)
    h = ring("h", [128, TT], 2)
    ot = ring("ot", [128, TT], 3)
    psR = k.ps("psR", [128, TT])
    psI = k.ps("psI", [128, TT])

    def item(n):
        c, pb = divmod(n, 2)
        prow = slice(pb * 128, (pb + 1) * 128)
        def T(lst, nm):
            j = n % len(lst)
            return lst[j], f'{nm}{j}'
        xh_, kxh = T(xh, 'xh'); gt_, kgt = T(gt, 'gt'); xc_, kxc = T(xc, 'xc'); r_, kr = T(r, 'r'); ig_, kig = T(ig, 'ig')
        a_, ka = T(a, 'a'); a2_, ka2 = T(a2, 'a2'); bt_, kbt = T(bt, 'bt'); g2_, kg2 = T(g2, 'g2'); h_, kh = T(h, 'h'); ot_, kot = T(ot, 'ot')
        if c == 0:
            k.memset('dve', xh_[:, 0:3], 0.0, [kxh + 'h'])
            k.dma('sp', xh_[:, 3:TT + 3], xbT[prow, 0:TT], w=[kxh])
        else:
            k.dma('sp', xh_[:, 0:TT + 3], xbT[prow, c * TT - 3:(c + 1) * TT], w=[kxh, kxh + 'h'])
        k.dma('sp', gt_[:], gateT[prow, c * TT:(c + 1) * TT], w=[kgt])
        yield
        xk = [kxh, kxh + 'h']
        k.ts('dve', xc_[:], xh_[:, 3:TT + 3], cw[:, pb, 3:4], cb[:, pb:pb + 1], ALU.mult, ALU.add, xk + ['cw', 'cb'], [kxc])
        for j in (2, 1, 0):
            k.stt(xc_[:], xh_[:, j:j + TT], cw[:, pb, j:j + 1], xc_[:], ALU.mult, ALU.add, xk + ['cw', kxc], [kxc])
        yield
        k.mm(psR[:], Wa[:, pb, :], xc_[:], True, True, ['Wa', kxc], ['psR'])
        k.mm(psI[:], Wx[:, pb, :], xc_[:], True, True, ['Wx', kxc], ['psI'])
        yield
        k.act(r_[:], psR[:], AF.Sigmoid, ['psR', 'ba'], [kr], bias=ba[:, pb:pb + 1])
        k.act(ig_[:], psI[:], AF.Sigmoid, ['psI', 'bx'], [kig], bias=bx[:, pb:pb + 1])
        k.act(a_[:], r_[:], AF.Exp, [kr, 'c8'], [ka], scale=c8[:, pb:pb + 1])
        k.act(a2_[:], a_[:], AF.Square, [ka], [ka2])
        k.act(a2_[:], a2_[:], AF.Sqrt, [ka2], [ka2], scale=-1.0, bias=1.0)
        k.act(g2_[:], gt_[:], AF.Square, [kgt], [kg2])
        k.act(g2_[:], g2_[:], AF.Copy, [kg2], [kg2], scale=0.044715, bias=1.0)
        yield
        k.tt('dve', bt_[:], ig_[:], xc_[:], ALU.mult, [kig, kxc], [kbt])
        k.tt('dve', bt_[:], bt_[:], a2_[:], ALU.mult, [kbt, ka2], [kbt])
        k.tt('dve', g2_[:], g2_[:], gt_[:], ALU.mult, [kg2, kgt], [kg2])
        yield
        k.act(g2_[:], g2_[:], AF.Sigmoid, [kg2], [kg2], scale=GELU_C)
        yield
        k.P.op('dve', lambda e: e.tensor_tensor_scan(out=h_[:], data0=a_[:], data1=bt_[:], initial=hlast[:, pb:pb + 1],
                                                     op0=ALU.mult, op1=ALU.add),
               reads=[ka, kbt, f'hlast{pb}'], writes=[kh])
        k.cp('dve', hlast[:, pb:pb + 1], h_[:, TT - 1:TT], [kh], [f'hlast{pb}'])
        k.tt('dve', g2_[:], g2_[:], gt_[:], ALU.mult, [kg2, kgt], [kg2])
        k.tt('dve', ot_[:], h_[:], g2_[:], ALU.mult, [kh, kg2], [kot])
        yield
        k.dma('pool', odT[prow, c * TT:(c + 1) * TT], ot_[:], r=[kot], final=True)

    yield from pipeline_gen(item, 2 * NCH)


def build_LRU(L, k=None):
    k = k or K()
    for _ in gen_LRU(L, k):
        pass
    return k.finish()


def lru_host_inputs(s, xb, gate, prm):
    cs = slice(256 * s, 256 * s + 256)
    col = lambda v: np.ascontiguousarray(v[cs].reshape(2, 128).T)
    Wa = np.zeros((2, 128, 128), np.float32)
    Wx = np.zeros((2, 128, 128), np.float32)
    for pb in range(2):
        for bl in range(2):
            blk = 4 * s + 2 * pb + bl
            Wa[pb, bl * 64:(bl + 1) * 64, bl * 64:(bl + 1) * 64] = prm['lru_w_a'][blk]
            Wx[pb, bl * 64:(bl + 1) * 64, bl * 64:(bl + 1) * 64] = prm['lru_w_x'][blk]
    cw = np.ascontiguousarray(prm['lru_conv_w'][:, cs].reshape(4, 2, 128).transpose(2, 1, 0))
    return dict(xbT=np.ascontiguousarray(xb[:, cs].T), gateT=np.ascontiguousarray(gate[:, cs].T), cw=cw,
                cb=col(prm['lru_conv_b']), Wa=Wa, Wx=Wx, ba=col(prm['lru_b_a']), bx=col(prm['lru_b_x']),
                lam=col(prm['lru_lambda']))


GN_EPS = 64e-5
NLEV = 5


def build_RWKV(L, k=None, NH=4, fr=False, CH=64):
    k = k or K()
    NT = L // 128
    W = NH * 64
    NG = NH // 4
    FR = mybir.dt.float32r if fr else F32
    rd = (lambda ap: ap.bitcast(F32)) if fr else (lambda ap: ap)
    NCK = 128 // CH
    nlev = 5 if CH == 64 else 6
    frc = fr and CH == 128
    FRC = mybir.dt.float32r if frc else F32
    rdc = (lambda ap: ap.bitcast(F32)) if frc else (lambda ap: ap)
    lhc = (lambda ap: ap) if frc else rd
    prkv = [k.din(nm, [L, W]) for nm in ("pr", "pk", "pv")]
    mu1 = k.din("mu1", [3 * W])
    pls = [k.din("plw", [64, L]), k.din("pla", [64, L]), k.din("plg", [128, L])]
    mul = k.din("mul", [128, 3])
    w2 = k.din("w2", [64, W])
    a2 = k.din("a2", [64, W])
    g2 = k.din("g2", [128, W])
    vecs = k.din("vecs", [7, W])
    ident_d = k.din("ident", [128, 128])
    triw_d = k.din("triw", [3, 128, 128])
    mask5_d = k.din("mask5", [128, 640])
    rowm_d = k.din("rowm", [128, 2])
    oc = k.dout("oc", [L, W])

    k.consts(ident_d)
    triw = k.sb("triw_s", [128, 3, 128])
    k.dma('sp', triw[:], triw_d.rearrange("a p n -> p a n"), w=['triw'])
    mask5 = k.sb("mask5_s", [128, 640])
    k.dma('sp', mask5[:], mask5_d, w=['mask5'])
    rowm = k.sb("rowm_s", [128, 2])
    k.dma('sp', rowm[:], rowm_d, w=['rowm'])
    mu1bc = k.bcast_row("mu1bc", mu1, 3 * W)
    vb = [k.bcast_row(f"vb{i}", vecs[i], W) for i in range(7)]
    w0bc, a0bc, kkbc, kabc, rkbc, lngbc, lnbbc = vb
    VK = [f"vb{i}" for i in range(7)]
    muls = k.sb("muls", [128, 3])
    k.dma('sp', muls[:], mul, w=['muls'])
    w2s = k.sb("w2s", [64, W])
    a2s = k.sb("a2s", [64, W])
    k.dma('sp', w2s[:], w2, w=['w2s'])
    k.dma('sp', a2s[:], a2, w=['a2s'])
    g2s = k.sb("g2s", [128, W])
    k.dma('sp', g2s[:], g2, w=['g2s'])
    ST = [k.sb(f"ST{i}", [64, 64], FRC) for i in range(NH)]
    zt = k.sb("zt", [128, W])
    k.memset('dve', zt[:], 0.0, ['zt'])
    for i in range(NH):
        k.cp('dve', ST[i][:], zt[0:64, 0:64], ['zt'], [f'ST{i}'])
    P1s = k.sb("P1s", [128, W], FRC)
    Us = k.sb("Us", [128, W], FRC)
    k.cp('dve', P1s[:], zt[:], ['zt'], ['P1s'])
    k.cp('dve', Us[:], zt[:], ['zt'], ['Us'])

    pt = [k.sb(f"pt{i}", [128, 3 * W]) for i in range(2)]
    pp = [k.sb(f"pp{i}", [128, 3 * W]) for i in range(2)]
    lt = [k.sb(f"lt{i}", [128, 3, 128]) for i in range(2)]
    lp = [k.sb(f"lp{i}", [128, 3, 128]) for i in range(2)]
    for i_ in range(2):
        k.memset('pool', lt[i_][:], 0.0, [f'lt{i_}0', f'lt{i_}1', f'lt{i_}2'])
        k.memset('pool', lp[i_][:], 0.0, [f'lp{i_}0', f'lp{i_}1', f'lp{i_}2', f'lp{i_}z'])
    pm = k.sb("pm", [128, 3 * W])
    vr = k.sb("vr", [128, W], FR)
    lm = k.sb("lm", [128, 3, 128])
    sw = k.sb("sw", [128, W])
    av = k.sb("av", [128, W])
    gv = k.sb("gv", [128, W])
    kkr = k.sb("kkr", [128, W])
    sq = k.sb("sq", [128, W])
    s4 = k.sb("s4", [128, NH])
    rn = k.sb("rn", [128, NH])
    nkk = k.sb("nkk", [128, W])
    kmod = k.sb("kmod", [128, W])
    kka = k.sb("kka", [128, W])
    tmp = k.sb("tmp", [128, W])
    bon = k.sb("bon", [128, NH])
    E1 = k.sb("E1", [128, W])
    E2 = k.sb("E2", [128, W])
    E3 = k.sb("E3", [128, W])
    E4 = k.sb("E4", [128, W])
    E1T = k.sb("E1T", [64, NH, 128])
    At = k.sb("At", [128, W])
    Bs = k.sb("Bs", [128, W])
    Ks = k.sb("Ks", [128, W])
    Rt = k.sb("Rt", [128, W])
    Bfm = [k.sb(f"Bfm{c}", [128, W]) for c in range(2)]
    Kfm = [k.sb(f"Kfm{c}", [128, W]) for c in range(2)]
    FT = [k.sb(f"FT{h}", [64, 4, 128], FR) for h in range(NH)]
    A5 = [k.sb(f"A5_{h}", [128, 640], FR) for h in range(NH)]
    NL = [k.sb(f"NL_{h}", [128, 256], FR) for h in range(NH)]
    PQ = [k.sb(f"PQ_{h}", [128, 256], FR) for h in range(NH)]
    W1 = k.sb("W1", [128, W], FR)
    U1 = k.sb("U1", [128, W])
    ysb = k.sb("ysb", [128, W])
    yc = k.sb("yc", [128, W])
    m4 = k.sb("m4", [128, NH])
    r4 = k.sb("r4", [128, NH])
    ot = [k.sb(f"ot{i}", [128, W]) for i in range(2)]
    B = [k.ps(f"psB{i}", [128, 512]) for i in range(8)]
    bk = lambda i: f'psB{i}'
    v3 = lambda t: t.rearrange("p (h j) -> p h j", h=NH)
    bc4 = lambda t: t.unsqueeze(2).broadcast_to([128, NH, 64])

    for i in range(NT):
        b = i % 2
        rows = slice(i * 128, (i + 1) * 128)
        PK, PPK, LTK, LPK = [], [], [], []
        for q in range(3):
            cq = slice(q * W, (q + 1) * W)
            k.dma('sp', pt[b][:, cq], prkv[q][rows, :], w=[f'pt{b}{q}'])
            PK.append(f'pt{b}{q}')
            if i == 0:
                k.dma('sp', pp[b][1:128, cq], prkv[q][0:127, :], w=[f'pp{b}{q}'])
            else:
                k.dma('sp', pp[b][:, cq], prkv[q][i * 128 - 1:i * 128 + 127, :], w=[f'pp{b}{q}'])
            PPK.append(f'pp{b}{q}')
            nr = pls[q].shape[0]
            k.dma('sp', lt[b][0:nr, q, :], pls[q][:, rows], w=[f'lt{b}{q}'])
            LTK.append(f'lt{b}{q}')
            if i == 0:
                k.dma('sp', lp[b][0:nr, q, 1:128], pls[q][:, 0:127], w=[f'lp{b}{q}'])
            else:
                k.dma('sp', lp[b][0:nr, q, :], pls[q][:, i * 128 - 1:i * 128 + 127], w=[f'lp{b}{q}'])
            LPK.append(f'lp{b}{q}')
        if i == 0:
            k.memset('pool', pp[b][0:1, :], 0.0, [f'pp{b}z'])
            k.memset('pool', lp[b][:, :, 0:1], 0.0, [f'lp{b}z'])
            PPK.append(f'pp{b}z')
            LPK.append(f'lp{b}z')
        k.tt('pool', pm[:], pp[b][:], pt[b][:], ALU.subtract, PPK + PK, ['pm'])
        k.tt('pool', pm[:], pm[:], mu1bc[:], ALU.mult, ['pm', 'mu1bc'], ['pm'])
        k.tt('pool', pm[:], pm[:], pt[b][:], ALU.add, ['pm'] + PK, ['pm'])
        r_, k_, v_ = pm[:, 0:W], pm[:, W:2 * W], pm[:, 2 * W:3 * W]
        k.cp('act', vr[:], v_, ['pm'], ['vr'])
        LK = LTK + LPK
        k.tt('dve', lm[:], lp[b][:], lt[b][:], ALU.subtract, LK, ['lm'])
        for blk in range(3):
            k.stt(lm[:, blk, :], lm[:, blk, :], muls[:, blk:blk + 1], lt[b][:, blk, :], ALU.mult, ALU.add,
                  ['lm', 'muls'] + LK, ['lm'])
        k.act(lm[0:64, 0, :], lm[0:64, 0, :], AF.Tanh, ['lm'], ['lm'])
        k.act(lm[:, 2, :], lm[:, 2, :], AF.Sigmoid, ['lm'], ['lm'])
        k.mm(B[0][:, 0:W], lm[0:64, 0, :], w2s[:], True, True, ['lm', 'w2s'], [bk(0)])
        k.mm(B[1][:, 0:W], lm[0:64, 1, :], a2s[:], True, True, ['lm', 'a2s'], [bk(1)])
        k.mm(B[2][:, 0:W], lm[:, 2, :], g2s[:], True, True, ['lm', 'g2s'], [bk(2)])
        k.tt('dve', sw[:], B[0][:, 0:W], w0bc[:], ALU.add, [bk(0), VK[0]], ['sw'])
        k.act(sw[:], sw[:], AF.Sigmoid, ['sw'], ['sw'])
        k.tt('dve', av[:], B[1][:, 0:W], a0bc[:], ALU.add, [bk(1), VK[1]], ['av'])
        k.act(av[:], av[:], AF.Sigmoid, ['av'], ['av'])
        k.cp('act', gv[:], B[2][:, 0:W], [bk(2)], ['gv'])
        k.tt('pool', kkr[:], k_, kkbc[:], ALU.mult, ['pm', VK[2]], ['kkr'])
        k.tt('pool', sq[:], kkr[:], kkr[:], ALU.mult, ['kkr'], ['sq'])
        k.P.op('dve', lambda e: e.tensor_reduce(out=s4[:], in_=v3(sq[:]), axis=AX.X, op=ALU.add), reads=['sq'], writes=['s4'])
        k.act(s4[:], s4[:], AF.Sqrt, ['s4'], ['s4'])
        k.ts('dve', s4[:], s4[:], 1e-12, None, ALU.max, None, ['s4'], ['s4'])
        k.recip(rn[:], s4[:], ['s4'], ['rn'])
        k.ts('dve', rn[:], rn[:], -1.0, None, ALU.mult, None, ['rn'], ['rn'])
        k.tt('dve', v3(nkk[:]), v3(kkr[:]), bc4(rn[:]), ALU.mult, ['kkr', 'rn'], ['nkk'])
        k.stt(tmp[:], av[:], -1.0, kabc[:], ALU.add, ALU.mult, ['av', VK[3]], ['tmp'])
        k.stt(kmod[:], tmp[:], 1.0, k_, ALU.add, ALU.mult, ['tmp', 'pm'], ['kmod'])
        k.stt(kka[:], nkk[:], -1.0, av[:], ALU.mult, ALU.mult, ['nkk', 'av'], ['kka'])
        k.tt('pool', tmp[:], r_, kmod[:], ALU.mult, ['pm', 'kmod', 'tmp'], ['tmp'])
        k.tt('pool', tmp[:], tmp[:], rkbc[:], ALU.mult, ['tmp', VK[4]], ['tmp'])
        k.P.op('dve', lambda e: e.tensor_reduce(out=bon[:], in_=v3(tmp[:]), axis=AX.X, op=ALU.add), reads=['tmp'], writes=['bon'])
        k.mm(B[3][:, 0:W], triw[:, 0, :], sw[:], True, True, ['triw', 'sw'], [bk(3)])
        k.mm(B[4][:, 0:W], triw[:, 1, :], sw[:], True, True, ['triw', 'sw'], [bk(4)])
        k.mm(B[5][:, 0:W], triw[:, 2, :], sw[:], True, True, ['triw', 'sw'], [bk(5)])
        for h in range(NH):
            k.mm(B[6 + h // 4][0:64, (h % 4) * 128:(h % 4 + 1) * 128], sw[:, h * 64:(h + 1) * 64], triw[:, 0, :], True, True,
                 ['sw', 'triw'], [bk(6 + h // 4)])
        k.act(E1[:], B[3][:, 0:W], AF.Exp, [bk(3)], ['E1'])
        k.act(E2[:], B[3][:, 0:W], AF.Exp, [bk(3)], ['E2'], scale=-1.0)
        k.act(E3[:], B[4][:, 0:W], AF.Exp, [bk(4)], ['E3'])
        k.act(E4[:], B[5][:, 0:W], AF.Exp, [bk(5)], ['E4'])
        for g in range(NG):
            k.act(E1T[:, 4 * g:4 * g + 4, :].rearrange("p a t -> p (a t)"), B[6 + g][0:64, :], AF.Exp, [bk(6 + g)], ['E1T'])
        k.tt('dve', At[:], nkk[:], E3[:], ALU.mult, ['nkk', 'E3'], ['At'])
        k.tt('pool', Bs[:], kka[:], E2[:], ALU.mult, ['kka', 'E2'], ['Bs'])
        k.tt('dve', Ks[:], kmod[:], E2[:], ALU.mult, ['kmod', 'E2'], ['Ks'])
        k.tt('pool', Rt[:], r_, E1[:], ALU.mult, ['pm', 'E1'], ['Rt'])
        for c in range(NCK):
            k.stt(Bfm[c][:], kka[:], rowm[:, c:c + 1], E4[:], ALU.mult, ALU.mult, ['kka', 'E4', 'rowm'], [f'Bfm{c}'])
            k.stt(Kfm[c][:], kmod[:], rowm[:, c:c + 1], E4[:], ALU.mult, ALU.mult, ['kmod', 'E4', 'rowm'], [f'Kfm{c}'])
        HS = list(range(NH))
        for h in HS:
            cs_ = slice(h * 64, (h + 1) * 64)
            for q, (src, key) in enumerate([(At, 'At'), (Bs, 'Bs'), (Ks, 'Ks'), (Rt, 'Rt')]):
                k.tr(B[h][0:64, q * 128:(q + 1) * 128], src[:, cs_], k.identf[:], [key], [bk(h)])
        for h in HS:
            k.cp('act' if h % 2 else 'dve', FT[h][:].rearrange("p a t -> p (a t)"), B[h][0:64, :], [bk(h)], [f'FT{h}'])
        for h in HS:
            AtT, BsT, KsT, RtT = (FT[h][:, q, :] for q in range(4))
            o = lambda j: B[h][:, j * 128:(j + 1) * 128]
            k.mm(o(0), BsT, AtT, True, True, [f'FT{h}'], [bk(h)])
            k.mm(o(1), AtT, BsT, True, True, [f'FT{h}'], [bk(h)])
            k.mm(o(2), KsT, AtT, True, True, [f'FT{h}'], [bk(h)])
        for h in HS:
            k.tt('dve', A5[h][:, 0:384], B[h][:, 0:384], mask5[:, 0:384], ALU.mult, [bk(h), 'mask5'], [f'A5_{h}'])
        for h in HS:
            AtT, BsT, KsT, RtT = (FT[h][:, q, :] for q in range(4))
            k.mm(B[h][:, 0:128], BsT, RtT, True, True, [f'FT{h}'], [bk(h)])
            k.mm(B[h][:, 128:256], KsT, RtT, True, True, [f'FT{h}'], [bk(h)])
        for h in HS:
            k.tt('dve', A5[h][:, 384:640], B[h][:, 0:256], mask5[:, 384:640], ALU.mult, [bk(h), 'mask5'], [f'A5b_{h}'])
            k.cp('act', NL[h][:], rd(A5[h][:, 0:256]), [f'A5_{h}'], [f'NL_{h}'])
            k.tt('pool' if not fr else 'dve', PQ[h][:].rearrange("p (a n) -> p a n", a=2), rd(A5[h][:, 0:256]).rearrange("p (a n) -> p a n", a=2),
                 k.identf[:].unsqueeze(1).broadcast_to([128, 2, 128]), ALU.add, [f'A5_{h}', 'ident'], [f'PQ_{h}'])
        for lev in range(nlev):
            for h in HS:
                N_, L_ = NL[h][:, 0:128], NL[h][:, 128:256]
                k.mm(B[h][:, 0:128], L_, N_, True, True, [f'NL_{h}'], [bk(h)])
                k.mm(B[h][:, 128:256], N_, L_, True, True, [f'NL_{h}'], [bk(h)])
            for h in HS:
                k.cp('act', NL[h][:], B[h][:, 0:256], [bk(h)], [f'NL_{h}'])
            for h in HS:
                N_, L_ = NL[h][:, 0:128], NL[h][:, 128:256]
                P_, Q_ = PQ[h][:, 0:128], PQ[h][:, 128:256]
                k.mm(B[h][:, 256:384], Q_, N_, True, True, [f'NL_{h}', f'PQ_{h}'], [bk(h)])
                k.mm(B[h][:, 384:512], P_, L_, True, True, [f'NL_{h}', f'PQ_{h}'], [bk(h)])
            for h in HS:
                k.tt('dve', PQ[h][:], B[h][:, 256:512], rd(PQ[h][:]), ALU.add, [bk(h), f'PQ_{h}'], [f'PQ_{h}'])
        for h in range(NH):
            k.mm(B[0][:, h * 64:(h + 1) * 64], A5[h][:, 256:384], vr[:, h * 64:(h + 1) * 64], True, True, [f'A5_{h}', 'vr'], [bk(0)])
        k.cp('act', W1[:], B[0][:, 0:W], [bk(0)], ['W1'])
        for h in range(NH):
            k.mm(B[1][:, h * 64:(h + 1) * 64], PQ[h][:, 0:128], W1[:, h * 64:(h + 1) * 64], True, True,
                 [f'PQ_{h}', 'W1'], [bk(1)])
        k.cp('act', U1[:], B[1][:, 0:W], [bk(1)], ['U1'])
        vsrc = vr if frc else None
        for c in range(NCK):
            cr = slice(c * CH, (c + 1) * CH)
            for h in range(NH):
                k.mm(B[2][cr, h * 64:(h + 1) * 64], lhc(FT[h][:, 0, cr]), ST[h][:], True, True, [f'FT{h}', f'ST{h}'], [bk(2)])
            k.cp('act', P1s[cr, :], B[2][cr, 0:W], [bk(2)], ['P1s'])
            for h in range(NH):
                k.mm(B[3][cr, h * 64:(h + 1) * 64], lhc(PQ[h][:, cr]), P1s[:, h * 64:(h + 1) * 64], True, True,
                     [f'PQ_{h}', 'P1s'], [bk(3)])
            k.tt('dve', Us[cr, :], B[3][cr, 0:W], U1[cr, :], ALU.add, [bk(3), 'U1'], ['Us'])
            for h in range(NH):
                hc_ = slice(h * 64, (h + 1) * 64)
                vh = vr[:, hc_] if frc else pm[:, 2 * W + h * 64:2 * W + (h + 1) * 64]
                vk = 'vr' if frc else 'pm'
                k.mm(B[6][cr, hc_], lhc(FT[h][:, 3, cr]), ST[h][:], True, False, [f'FT{h}', f'ST{h}'], [bk(6)])
                k.mm(B[6][cr, hc_], lhc(A5[h][:, 384:512][:, cr]), Us[:, hc_], False, False, [f'A5b_{h}', 'Us'], [bk(6)])
                k.mm(B[6][cr, hc_], lhc(A5[h][:, 512:640][:, cr]), vh, False, True, [f'A5b_{h}', vk], [bk(6)])
            for h in range(NH):
                hc_ = slice(h * 64, (h + 1) * 64)
                vh = pm[:, 2 * W + h * 64:2 * W + (h + 1) * 64]
                k.mm(B[7][0:64, hc_], Bfm[c][:, hc_], rdc(Us[:, hc_]), True, False, [f'Bfm{c}', 'Us'], [bk(7)])
                k.mm(B[7][0:64, hc_], Kfm[c][:, hc_], vh, False, True, [f'Kfm{c}', 'pm'], [bk(7)])
            for h in range(NH):
                hc_ = slice(h * 64, (h + 1) * 64)
                k.stt(ST[h][:], rdc(ST[h][:]), E1T[:, h, (c + 1) * CH - 1:(c + 1) * CH], B[7][0:64, hc_], ALU.mult, ALU.add,
                      [f'ST{h}', 'E1T', bk(7)], [f'ST{h}'])
        k.cp('act', ysb[:], B[6][:, 0:W], [bk(6)], ['ysb'])
        k.P.op('dve', lambda e: e.tensor_reduce(out=m4[:], in_=v3(ysb[:]), axis=AX.X, op=ALU.add), reads=['ysb'], writes=['m4'])
        k.ts('dve', m4[:], m4[:], -1.0 / 64.0, None, ALU.mult, None, ['m4'], ['m4'])
        k.tt('dve', v3(yc[:]), v3(ysb[:]), bc4(m4[:]), ALU.add, ['ysb', 'm4'], ['yc'])
        k.tt('pool', sq[:], yc[:], yc[:], ALU.mult, ['yc'], ['sq'])
        k.P.op('dve', lambda e: e.tensor_reduce(out=r4[:], in_=v3(sq[:]), axis=AX.X, op=ALU.add), reads=['sq'], writes=['r4'])
        k.ts('dve', r4[:], r4[:], 1.0 / 64.0, GN_EPS, ALU.mult, ALU.add, ['r4'], ['r4'])
        k.act(r4[:], r4[:], AF.Sqrt, ['r4'], ['r4'])
        k.recip(r4[:], r4[:], ['r4'], ['r4'])
        k.tt('dve', v3(yc[:]), v3(yc[:]), bc4(r4[:]), ALU.mult, ['yc', 'r4'], ['yc'])
        k.tt('pool', yc[:], yc[:], lngbc[:], ALU.mult, ['yc', VK[5]], ['yc'])
        k.tt('pool', yc[:], yc[:], lnbbc[:], ALU.add, ['yc', VK[6]], ['yc'])
        k.tt('dve', v3(tmp[:]), v3(v_), bc4(bon[:]), ALU.mult, ['pm', 'bon', 'tmp'], ['tmp'])
        k.tt('pool', yc[:], yc[:], tmp[:], ALU.add, ['yc', 'tmp'], ['yc'])
        k.tt('dve', ot[b][:], yc[:], gv[:], ALU.mult, ['yc', 'gv'], [f'ot{b}'])
        k.dma('pool', oc[rows, :], ot[b][:], r=[f'ot{b}'], final=True)
    return k.finish()


def build_RWKVP(L, k=None, CH=64):
    NH, fr = 8, True
    k = k or K()
    NT = L // 128
    W = NH * 64
    NG = NH // 4
    FR = mybir.dt.float32r if fr else F32
    rd = (lambda ap: ap.bitcast(F32)) if fr else (lambda ap: ap)
    NCK = 128 // CH
    nlev = 5 if CH == 64 else 6
    frc = True
    FRC = mybir.dt.float32r if frc else F32
    rdc = (lambda ap: ap.bitcast(F32)) if frc else (lambda ap: ap)
    lhc = (lambda ap: ap) if frc else rd
    prkv = [k.din(nm, [L, W]) for nm in ("pr", "pk", "pv")]
    mu1 = k.din("mu1", [3 * W])
    pls = [k.din("plw", [64, L]), k.din("pla", [64, L]), k.din("plg", [128, L])]
    mul = k.din("mul", [128, 3])
    w2 = k.din("w2", [64, W])
    a2 = k.din("a2", [64, W])
    g2 = k.din("g2", [128, W])
    vecs = k.din("vecs", [7, W])
    ident_d = k.din("ident", [128, 128])
    triw_d = k.din("triw", [3, 128, 128])
    mask5_d = k.din("mask5", [128, 640])
    rowm_d = k.din("rowm", [128, 2])
    oc = k.dout("oc", [L, W])

    k.consts(ident_d)
    triw = k.sb("triw_s", [128, 3, 128])
    k.dma('sp', triw[:], triw_d.rearrange("a p n -> p a n"), w=['triw'])
    mask5 = k.sb("mask5_s", [128, 640])
    k.dma('sp', mask5[:], mask5_d, w=['mask5'])
    rowm = k.sb("rowm_s", [128, 2])
    k.dma('sp', rowm[:], rowm_d, w=['rowm'])
    mu1bc = k.bcast_row("mu1bc", mu1, 3 * W)
    vb = [k.bcast_row(f"vb{i}", vecs[i], W) for i in range(7)]
    w0bc, a0bc, kkbc, kabc, rkbc, lngbc, lnbbc = vb
    VK = [f"vb{i}" for i in range(7)]
    muls = k.sb("muls", [128, 3])
    k.dma('sp', muls[:], mul, w=['muls'])
    w2s = k.sb("w2s", [64, W])
    a2s = k.sb("a2s", [64, W])
    k.dma('sp', w2s[:], w2, w=['w2s'])
    k.dma('sp', a2s[:], a2, w=['a2s'])
    g2s = k.sb("g2s", [128, W])
    k.dma('sp', g2s[:], g2, w=['g2s'])
    ST = [k.sb(f"ST{i}", [64, 64], FRC) for i in range(NH)]
    zt = k.sb("zt", [128, W])
    k.memset('dve', zt[:], 0.0, ['zt'])
    for i in range(NH):
        k.cp('dve', ST[i][:], zt[0:64, 0:64], ['zt'], [f'ST{i}'])
    P1s = k.sb("P1s", [128, W], FRC)
    Us = k.sb("Us", [128, W], FRC)
    k.cp('dve', P1s[:], zt[:], ['zt'], ['P1s'])
    k.cp('dve', Us[:], zt[:], ['zt'], ['Us'])

    pt = [k.sb("pt0", [128, 3 * W])] * 2
    pp = [k.sb("pp0", [128, 3 * W])] * 2
    lt = [k.sb("lt0", [128, 3, 128])] * 2
    lp = [k.sb("lp0", [128, 3, 128])] * 2
    k.memset('pool', lt[0][:], 0.0, ['lt0', 'lt1', 'lt2'])
    k.memset('pool', lp[0][:], 0.0, ['lp0', 'lp1', 'lp2', 'lpz'])
    pm2 = [k.sb(f"pm{i_}", [128, 3 * W]) for i_ in range(2)]
    vr2 = [k.sb(f"vr{i_}", [128, W], FR) for i_ in range(2)]
    lm2 = [k.sb(f"lm{i_}", [128, 3, 128]) for i_ in range(2)]
    sw = k.sb("sw", [128, W])
    av = k.sb("av", [128, W])
    gv2 = [k.sb(f"gv{i_}", [128, W]) for i_ in range(2)]
    kkr = k.sb("kkr", [128, W])
    sq = k.sb("sq", [128, W])
    s4 = k.sb("s4", [128, NH])
    rn = k.sb("rn", [128, NH])
    nkk = k.sb("nkk", [128, W])
    kmod = k.sb("kmod", [128, W])
    kka = k.sb("kka", [128, W])
    tmp = k.sb("tmp", [128, W])
    bon2 = [k.sb(f"bon{i_}", [128, NH]) for i_ in range(2)]
    E1 = k.sb("E1", [128, W])
    E2 = k.sb("E2", [128, W])
    E3 = k.sb("E3", [128, W])
    E4 = k.sb("E4", [128, W])
    E1T2 = [k.sb(f"E1T{i_}", [64, NH, 128]) for i_ in range(2)]
    At2 = [k.sb(f"At{i_}", [128, W]) for i_ in range(2)]
    Bs2 = [k.sb(f"Bs{i_}", [128, W]) for i_ in range(2)]
    Ks2 = [k.sb(f"Ks{i_}", [128, W]) for i_ in range(2)]
    Rt2 = [k.sb(f"Rt{i_}", [128, W]) for i_ in range(2)]
    Bfm2 = [[k.sb(f"Bfm{p_}{c}", [128, W]) for c in range(NCK)] for p_ in range(2)]
    Kfm2 = [[k.sb(f"Kfm{p_}{c}", [128, W]) for c in range(NCK)] for p_ in range(2)]
    sqp = k.sb("sqp", [128, W])
    tmpp = k.sb("tmpp", [128, W])
    FT = [k.sb(f"FT{h}", [64, 4, 128], FR) for h in range(NH)]
    A5 = [k.sb(f"A5_{h}", [128, 640], FR) for h in range(NH)]
    NL = [k.sb(f"NL_{h}", [128, 256], FR) for h in range(NH)]
    PQ = [k.sb(f"PQ_{h}", [128, 128], FR) for h in range(NH)]
    W1 = k.sb("W1", [128, W], FR)
    U1 = k.sb("U1", [128, W])
    ysb = k.sb("ysb", [128, W])
    yc = k.sb("yc", [128, W])
    m4 = k.sb("m4", [128, NH])
    r4 = k.sb("r4", [128, NH])
    ot = [k.sb(f"ot{i}", [128, W]) for i in range(2)]
    B = [k.ps(f"psB{i}", [128, 512]) for i in range(8)]
    bk = lambda i: f'psB{i}'
    v3 = lambda t: t.rearrange("p (h j) -> p h j", h=NH)
    bc4 = lambda t: t.unsqueeze(2).broadcast_to([128, NH, 64])


    S0, S1, C0, C1 = 6, 7, 4, 5

    def tile(i):
        b = i % 2
        pm, lm = pm2[b], lm2[b]
        kpm, klm = f'pm{b}', f'lm{b}'
        At, Bs, Ks, Rt, gv, vr, bon, E1T, Bf, Kf = At2[b], Bs2[b], Ks2[b], Rt2[b], gv2[b], vr2[b], bon2[b], E1T2[b], Bfm2[b], Kfm2[b]
        kAt, kBs, kKs, kRt, kgv, kvr, kbon, kE1T, kBf, kKf = (f'{n_}{b}' for n_ in ('At', 'Bs', 'Ks', 'Rt', 'gv', 'vr', 'bon', 'E1T', 'Bf', 'Kf'))
        rows = slice(i * 128, (i + 1) * 128)
        PK, PPK, LTK, LPK = [], [], [], []
        for q in range(3):
            cq = slice(q * W, (q + 1) * W)
            k.dma('sp', pt[b][:, cq], prkv[q][rows, :], w=[f'pt{q}'])
            PK.append(f'pt{q}')
            if i == 0:
                k.dma('sp', pp[b][1:128, cq], prkv[q][0:127, :], w=[f'pp{q}'])
            else:
                k.dma('sp', pp[b][:, cq], prkv[q][i * 128 - 1:i * 128 + 127, :], w=[f'pp{q}'])
            PPK.append(f'pp{q}')
            nr = pls[q].shape[0]
            k.dma('sp', lt[b][0:nr, q, :], pls[q][:, rows], w=[f'lt{q}'])
            LTK.append(f'lt{q}')
            if i == 0:
                k.dma('sp', lp[b][0:nr, q, 1:128], pls[q][:, 0:127], w=[f'lp{q}'])
            else:
                k.dma('sp', lp[b][0:nr, q, :], pls[q][:, i * 128 - 1:i * 128 + 127], w=[f'lp{q}'])
            LPK.append(f'lp{q}')
        if i == 0:
            k.memset('pool', pp[b][0:1, :], 0.0, ['ppz'])
            k.memset('pool', lp[b][:, :, 0:1], 0.0, ['lpz'])
            PPK.append('ppz')
            LPK.append('lpz')
        k.tt('dve', pm[:], pp[b][:], pt[b][:], ALU.subtract, PPK + PK, [kpm])
        k.tt('dve', pm[:], pm[:], mu1bc[:], ALU.mult, [kpm, 'mu1bc'], [kpm])
        k.tt('dve', pm[:], pm[:], pt[b][:], ALU.add, [kpm] + PK, [kpm])
        r_, k_, v_ = pm[:, 0:W], pm[:, W:2 * W], pm[:, 2 * W:3 * W]
        LK = LTK + LPK
        k.tt('dve', lm[:], lp[b][:], lt[b][:], ALU.subtract, LK, [klm])
        for blk in range(3):
            k.stt(lm[:, blk, :], lm[:, blk, :], muls[:, blk:blk + 1], lt[b][:, blk, :], ALU.mult, ALU.add,
                  [klm, 'muls'] + LK, [klm])
        k.act(lm[0:64, 0, :], lm[0:64, 0, :], AF.Tanh, [klm], [klm])
        k.act(lm[:, 2, :], lm[:, 2, :], AF.Sigmoid, [klm], [klm])
        yield 'STAGE'
        k.cp('act', vr[:], v_, [kpm], [kvr])
        k.mm(B[S0][:, 0:W], lm[0:64, 0, :], w2s[:], True, True, [klm, 'w2s'], [bk(S0)])
        k.mm(B[S1][:, 0:W], lm[0:64, 1, :], a2s[:], True, True, [klm, 'a2s'], [bk(S1)])
        yield 'sub'
        k.tt('dve', sw[:], B[S0][:, 0:W], w0bc[:], ALU.add, [bk(S0), VK[0]], ['sw'])
        k.act(sw[:], sw[:], AF.Sigmoid, ['sw'], ['sw'])
        k.tt('dve', av[:], B[S1][:, 0:W], a0bc[:], ALU.add, [bk(S1), VK[1]], ['av'])
        k.act(av[:], av[:], AF.Sigmoid, ['av'], ['av'])
        yield 'sub'
        k.mm(B[S0][:, 0:W], lm[:, 2, :], g2s[:], True, True, [klm, 'g2s'], [bk(S0)])
        k.cp('act', gv[:], B[S0][:, 0:W], [bk(S0)], [kgv])
        yield 'sub'
        k.tt('dve', kkr[:], k_, kkbc[:], ALU.mult, [kpm, VK[2]], ['kkr'])
        k.tt('dve', sq[:], kkr[:], kkr[:], ALU.mult, ['kkr'], ['sq'])
        k.P.op('dve', lambda e: e.tensor_reduce(out=s4[:], in_=v3(sq[:]), axis=AX.X, op=ALU.add), reads=['sq'], writes=['s4'])
        k.act(s4[:], s4[:], AF.Sqrt, ['s4'], ['s4'])
        k.ts('dve', s4[:], s4[:], 1e-12, None, ALU.max, None, ['s4'], ['s4'])
        k.recip(rn[:], s4[:], ['s4'], ['rn'])
        k.ts('dve', rn[:], rn[:], -1.0, None, ALU.mult, None, ['rn'], ['rn'])
        k.tt('dve', v3(nkk[:]), v3(kkr[:]), bc4(rn[:]), ALU.mult, ['kkr', 'rn'], ['nkk'])
        k.stt(tmp[:], av[:], -1.0, kabc[:], ALU.add, ALU.mult, ['av', VK[3]], ['tmp'])
        k.stt(kmod[:], tmp[:], 1.0, k_, ALU.add, ALU.mult, ['tmp', kpm], ['kmod'])
        k.stt(kka[:], nkk[:], -1.0, av[:], ALU.mult, ALU.mult, ['nkk', 'av'], ['kka'])
        k.tt('dve', tmp[:], r_, kmod[:], ALU.mult, [kpm, 'kmod', 'tmp'], ['tmp'])
        k.tt('dve', tmp[:], tmp[:], rkbc[:], ALU.mult, ['tmp', VK[4]], ['tmp'])
        k.P.op('dve', lambda e: e.tensor_reduce(out=bon[:], in_=v3(tmp[:]), axis=AX.X, op=ALU.add), reads=['tmp'], writes=[kbon])
        yield 'sub'
        k.mm(B[S1][:, 0:W], triw[:, 0, :], sw[:], True, True, ['triw', 'sw'], [bk(S1)])
        k.mm(B[S0][:, 0:W], triw[:, 1, :], sw[:], True, True, ['triw', 'sw'], [bk(S0)])
        yield 'sub'
        k.act(E1[:], B[S1][:, 0:W], AF.Exp, [bk(S1)], ['E1'])
        k.act(E2[:], B[S1][:, 0:W], AF.Exp, [bk(S1)], ['E2'], scale=-1.0)
        k.act(E3[:], B[S0][:, 0:W], AF.Exp, [bk(S0)], ['E3'])
        k.mm(B[S1][:, 0:W], triw[:, 2, :], sw[:], True, True, ['triw', 'sw'], [bk(S1)])
        k.act(E4[:], B[S1][:, 0:W], AF.Exp, [bk(S1)], ['E4'])
        yield 'sub'
        for g in range(2):
            for hl in range(4):
                h = 4 * g + hl
                k.mm(B[S0 + g][0:64, hl * 128:(hl + 1) * 128], sw[:, h * 64:(h + 1) * 64], triw[:, 0, :], True, True,
                     ['sw', 'triw'], [bk(S0 + g)])
        yield 'sub'
        for g in range(2):
            k.act(E1T[:, 4 * g:4 * g + 4, :].rearrange("p a t -> p (a t)"), B[S0 + g][0:64, :], AF.Exp, [bk(S0 + g)], [kE1T])
        yield 'sub'
        k.tt('dve', At[:], nkk[:], E3[:], ALU.mult, ['nkk', 'E3'], [kAt])
        k.tt('dve', Bs[:], kka[:], E2[:], ALU.mult, ['kka', 'E2'], [kBs])
        k.tt('dve', Ks[:], kmod[:], E2[:], ALU.mult, ['kmod', 'E2'], [kKs])
        k.tt('dve', Rt[:], r_, E1[:], ALU.mult, [kpm, 'E1'], [kRt])
        for c in range(NCK):
            k.stt(Bf[c][:], kka[:], rowm[:, c:c + 1], E4[:], ALU.mult, ALU.mult, ['kka', 'E4', 'rowm'], [kBf])
            k.stt(Kf[c][:], kmod[:], rowm[:, c:c + 1], E4[:], ALU.mult, ALU.mult, ['kmod', 'E4', 'rowm'], [kKf])
        yield 'STAGE'
        for g in range(1):
            HS = list(range(8))
            for h in HS:
                hl = h
                cs_ = slice(h * 64, (h + 1) * 64)
                for q, (src, key) in enumerate([(At, kAt), (Bs, kBs), (Ks, kKs), (Rt, kRt)]):
                    k.tr(B[hl][0:64, q * 128:(q + 1) * 128], src[:, cs_], k.identf[:], [key], [bk(hl)])
            for h in HS:
                hl = h
                k.cp('act' if h % 2 else 'dve', FT[h][:].rearrange("p a t -> p (a t)"), B[hl][0:64, :], [bk(hl)], [f'FT{h}'])
            for h in HS:
                hl = h
                AtT, BsT, KsT, RtT = (FT[h][:, q, :] for q in range(4))
                k.mm(B[hl][:, 0:128], BsT, AtT, True, True, [f'FT{h}'], [bk(hl)])
                k.mm(B[hl][:, 128:256], AtT, BsT, True, True, [f'FT{h}'], [bk(hl)])
                k.mm(B[hl][:, 256:384], KsT, AtT, True, True, [f'FT{h}'], [bk(hl)])
            for h in HS:
                hl = h
                k.tt('dve', A5[h][:, 0:384], B[hl][:, 0:384], mask5[:, 0:384], ALU.mult, [bk(hl), 'mask5'], [f'A5_{h}'])
            for h in HS:
                hl = h
                AtT, BsT, KsT, RtT = (FT[h][:, q, :] for q in range(4))
                k.mm(B[hl][:, 0:128], BsT, RtT, True, True, [f'FT{h}'], [bk(hl)])
                k.mm(B[hl][:, 128:256], KsT, RtT, True, True, [f'FT{h}'], [bk(hl)])
            for h in HS:
                hl = h
                k.tt('dve', A5[h][:, 384:640], B[hl][:, 0:256], mask5[:, 384:640], ALU.mult, [bk(hl), 'mask5'], [f'A5b_{h}'])
                k.cp('act', NL[h][:], rd(A5[h][:, 0:256]), [f'A5_{h}'], [f'NL_{h}'])
                k.tt('dve', PQ[h][:, 0:128], rd(A5[h][:, 0:128]), k.identf[:], ALU.add, [f'A5_{h}', 'ident'], [f'PQ_{h}'])
            for lev in range(nlev):
                last = (lev == nlev - 1)
                for h in HS:
                    hl = h
                    N_, L_ = NL[h][:, 0:128], NL[h][:, 128:256]
                    k.mm(B[hl][:, 0:128], L_, N_, True, True, [f'NL_{h}'], [bk(hl)])
                    k.mm(B[hl][:, 128:256], N_, L_, True, True, [f'NL_{h}'], [bk(hl)])
                for h in HS:
                    hl = h
                    k.cp('act', NL[h][:], B[hl][:, 0:256], [bk(hl)], [f'NL_{h}'])
                for h in HS:
                    hl = h
                    k.mm(B[hl][:, 256:384], NL[h][:, 128:256], PQ[h][:, 0:128], True, True, [f'NL_{h}', f'PQ_{h}'], [bk(hl)])
                for h in HS:
                    hl = h
                    k.tt('dve', PQ[h][:, 0:128], B[hl][:, 256:384], rd(PQ[h][:, 0:128]), ALU.add, [bk(hl), f'PQ_{h}'], [f'PQ_{h}'])
            yield 'GROUP'
        for h in range(NH):
            k.mm(B[C0][:, h * 64:(h + 1) * 64], A5[h][:, 256:384], vr[:, h * 64:(h + 1) * 64], True, True, [f'A5_{h}', kvr], [bk(C0)])
        yield 'sub'
        k.cp('act', W1[:], B[C0][:, 0:W], [bk(C0)], ['W1'])
        yield 'sub'
        for h in range(NH):
            k.mm(B[C1][:, h * 64:(h + 1) * 64], PQ[h][:, 0:128], W1[:, h * 64:(h + 1) * 64], True, True,
                 [f'PQ_{h}', 'W1'], [bk(C1)])
        yield 'sub'
        k.cp('act', U1[:], B[C1][:, 0:W], [bk(C1)], ['U1'])
        yield 'sub'
        for c in range(NCK):
            cr = slice(c * CH, (c + 1) * CH)
            for h in range(NH):
                k.mm(B[C0][:, h * 64:(h + 1) * 64], FT[h][:, 0, :], ST[h][:], True, True, [f'FT{h}', f'ST{h}'], [bk(C0)])
            yield 'sub'
            k.cp('act', P1s[cr, :], B[C0][cr, 0:W], [bk(C0)], ['P1s'])
            yield 'sub'
            for h in range(NH):
                k.mm(B[C0][:, h * 64:(h + 1) * 64], PQ[h][:, :], P1s[:, h * 64:(h + 1) * 64], True, True,
                     [f'PQ_{h}', 'P1s'], [bk(C0)])
            yield 'sub'
            k.tt('dve', Us[cr, :], B[C0][cr, 0:W], U1[cr, :], ALU.add, [bk(C0), 'U1'], ['Us'])
            yield 'sub'
            for h in range(NH):
                hc_ = slice(h * 64, (h + 1) * 64)
                k.mm(B[C0][:, hc_], FT[h][:, 3, :], ST[h][:], True, False, [f'FT{h}', f'ST{h}'], [bk(C0)])
                k.mm(B[C0][:, hc_], A5[h][:, 384:512], Us[:, hc_], False, False, [f'A5b_{h}', 'Us'], [bk(C0)])
                k.mm(B[C0][:, hc_], A5[h][:, 512:640], vr[:, hc_], False, True, [f'A5b_{h}', kvr], [bk(C0)])
            yield 'sub'
            k.cp('act', ysb[cr, :], B[C0][cr, 0:W], [bk(C0)], ['ysb'])
            for h in range(NH):
                hc_ = slice(h * 64, (h + 1) * 64)
                k.mm(B[C1][0:64, hc_], Bf[c][:, hc_], rdc(Us[:, hc_]), True, False, [kBf, 'Us'], [bk(C1)])
                k.mm(B[C1][0:64, hc_], Kf[c][:, hc_], rd(vr[:, hc_]), False, True, [kKf, kvr], [bk(C1)])
            yield 'sub'
            for h in range(NH):
                hc_ = slice(h * 64, (h + 1) * 64)
                k.stt(ST[h][:], rdc(ST[h][:]), E1T[:, h, (c + 1) * CH - 1:(c + 1) * CH], B[C1][0:64, hc_], ALU.mult, ALU.add,
                      [f'ST{h}', kE1T, bk(C1)], [f'ST{h}'])
        k.P.op('dve', lambda e: e.tensor_reduce(out=m4[:], in_=v3(ysb[:]), axis=AX.X, op=ALU.add), reads=['ysb'], writes=['m4'])
        k.ts('dve', m4[:], m4[:], -1.0 / 64.0, None, ALU.mult, None, ['m4'], ['m4'])
        k.tt('dve', v3(yc[:]), v3(ysb[:]), bc4(m4[:]), ALU.add, ['ysb', 'm4'], ['yc'])
        k.tt('dve', sqp[:], yc[:], yc[:], ALU.mult, ['yc'], ['sqp'])
        k.P.op('dve', lambda e: e.tensor_reduce(out=r4[:], in_=v3(sqp[:]), axis=AX.X, op=ALU.add), reads=['sqp'], writes=['r4'])
        k.ts('dve', r4[:], r4[:], 1.0 / 64.0, GN_EPS, ALU.mult, ALU.add, ['r4'], ['r4'])
        k.act(r4[:], r4[:], AF.Sqrt, ['r4'], ['r4'])
        k.recip(r4[:], r4[:], ['r4'], ['r4'])
        k.tt('dve', v3(yc[:]), v3(yc[:]), bc4(r4[:]), ALU.mult, ['yc', 'r4'], ['yc'])
        k.tt('dve', yc[:], yc[:], lngbc[:], ALU.mult, ['yc', VK[5]], ['yc'])
        k.tt('dve', yc[:], yc[:], lnbbc[:], ALU.add, ['yc', VK[6]], ['yc'])
        k.tt('dve', v3(tmpp[:]), v3(rd(vr[:])), bc4(bon[:]), ALU.mult, [kvr, kbon], ['tmpp'])
        k.tt('dve', yc[:], yc[:], tmpp[:], ALU.add, ['yc', 'tmpp'], ['yc'])
        k.tt('dve', ot[b][:], yc[:], gv[:], ALU.mult, ['yc', kgv], [f'ot{b}'])
        k.dma('pool', oc[rows, :], ot[b][:], r=[f'ot{b}'], final=True)

    gens = {}
    done = set()

    def adv(j):
        try:
            return next(gens[j])
        except StopIteration:
            done.add(j)
            return 'END'

    for step in range(NT + 2):
        if step < NT:
            gens[step] = tile(step)
            while adv(step) != 'STAGE':
                pass
        jb = step - 2
        if 0 <= jb < NT:
            n_g = 0
            while n_g < 1:
                if adv(jb) == 'GROUP':
                    n_g += 1
        ja = step - 1
        a_live = 0 <= ja < NT
        b_live = 0 <= jb < NT
        while a_live or b_live:
            if a_live:
                if adv(ja) == 'STAGE':
                    a_live = False
            if b_live:
                if adv(jb) == 'END':
                    b_live = False
    return k.finish()


def rwkv_consts(CH=64):
    c = -math.exp(-0.5)
    blk = np.kron(np.eye(128 // CH), np.ones((CH, CH)))
    s_idx = np.arange(128)[:, None]
    t_idx = np.arange(128)[None, :]
    triw = np.stack([c * blk * (s_idx <= t_idx), c * blk * (s_idx < t_idx), c * blk * (s_idx > t_idx)]).astype(np.float32)
    lt_, le_, gt_ = blk * (s_idx < t_idx), blk * (s_idx <= t_idx), blk * (t_idx < s_idx)
    mask5 = np.concatenate([lt_, gt_, lt_, le_, le_], 1).astype(np.float32)
    rowm = np.stack([(np.arange(128) < 64), (np.arange(128) >= 64)], 1).astype(np.float32) if CH == 64 else np.ones((128, 2), np.float32)
    return dict(ident=np.eye(128, dtype=np.float32), triw=triw, mask5=mask5, rowm=rowm)


def rwkv_host_inputs(s, p_rwkv, prm, NH=4, CH=64):
    L = p_rwkv.shape[0]
    cs = slice(64 * NH * s, 64 * NH * (s + 1))
    r_, w1, k_, v_, a1, g1 = np.split(p_rwkv, np.cumsum([512, 64, 512, 512, 64])[:5], axis=-1)
    mu = prm['rwkv_mu']
    mur, muw1, muk, muv, mua1, mug1 = np.split(mu, np.cumsum([512, 64, 512, 512, 64])[:5])
    zm = np.zeros(64, np.float32)
    mul = np.concatenate([muw1, zm, mua1, zm, mug1]).reshape(3, 128).T
    vecs = np.stack([prm['rwkv_w0'][cs], prm['rwkv_a0'][cs], prm['rwkv_k_k'][cs], prm['rwkv_k_a'][cs],
                     prm['rwkv_r_k'].reshape(-1)[cs], prm['rwkv_ln_gain'][cs], prm['rwkv_ln_bias'][cs]])
    c_ = np.ascontiguousarray
    d = dict(pr=c_(r_[:, cs]), pk=c_(k_[:, cs]), pv=c_(v_[:, cs]),
             mu1=c_(np.concatenate([mur[cs], muk[cs], muv[cs]])),
             plw=c_(w1.T), pla=c_(a1.T), plg=c_(g1.T), mul=c_(mul),
             w2=c_(prm['rwkv_w2'][:, cs]), a2=c_(prm['rwkv_a2'][:, cs]),
             g2=c_(prm['rwkv_g2'][:, cs]), vecs=c_(vecs))
    d.update(rwkv_consts(CH))
    return d


FM0 = [(0, 128, 0), (128, 128, 128), (256, 128, 256), (384, 128, 384), (1536, 16, 512)] + \
      [(1552 + j * 128, 128, 528 + j * 128) for j in range(4)]
NF0 = 1040
FM1 = [(512, 64, 0), (1600, 64, 64), (1664, 128, 128)] + [(1792 + j * 128, 128, 256 + j * 128) for j in range(8)]
NF1 = 1280


def host_params(inp):
    c_ = lambda a: np.ascontiguousarray(np.asarray(a), dtype=np.float32)
    P = {}
    P['ident'] = np.eye(128, dtype=np.float32)
    P['triu'] = np.triu(np.ones((128, 128), np.float32))
    P['trigt'] = np.tril(np.ones((128, 128), np.float32), -1)
    for l in range(2):
        for j in range(7):
            P[f'g{l}_{j}'] = c_(inp['norm_gain'][l][j])
        for nm in ('xa_wq', 'xa_wk', 'xa_wv', 'xa_wo', 'mlp_w1', 'mlp_w2'):
            P[f'{nm}{l}'] = c_(inp[nm][l])
    P['w_in0'] = c_(inp['ab_w_in'][0])
    P['w_in1'] = c_(inp['cd_w_in'][0])
    P['w_out0'] = c_(inp['ab_w_out'][0])
    P['w_out1'] = c_(inp['cd_w_out'][0])
    P['wglu'] = c_(inp['s5_w_glu'][0])
    P['bglu'] = c_(inp['s5_b_glu'][0])
    prm0 = {k_: np.asarray(inp[k_][0]) for k_ in inp if k_.startswith('s5_') or k_.startswith('gla_')}
    prm1 = {k_: np.asarray(inp[k_][0]) for k_ in inp if k_.startswith('rwkv_') or k_.startswith('lru_')}
    for s in range(2):
        cs = slice(s * 128, (s + 1) * 128)
        P[f'gla_w2_{s}'] = c_(prm0['gla_w_decay2'][:, cs])
        P[f'gla_bd_{s}'] = c_(prm0['gla_b_decay'][None, cs])
        P[f'gla_gn_{s}'] = c_(prm0['gla_norm_gain'][2 * s:2 * s + 2].reshape(256))
        d = s5_host_inputs(s, np.zeros((2, 512), np.float32), prm0)
        for nm in ('lam_re', 'lam_im', 'lstep', 'Bre', 'Bim', 'Cre', 'Cim', 'dsk'):
            P[f's5_{nm}_{s}'] = c_(d[nm])
        P['iota_p'] = c_(d['iota_p'])
        P['iota_f'] = c_(d['iota_f'])
        if s == 0:
            d = rwkv_host_inputs(0, np.zeros((2, 1792), np.float32), prm1, 8, 64)
            for nm in ('mu1', 'mul', 'w2', 'a2', 'g2', 'vecs'):
                P[f'rw_{nm}'] = c_(d[nm])
            for nm in ('triw', 'mask5', 'rowm'):
                P[f'rw_{nm}'] = c_(d[nm])
        d = lru_host_inputs(s, np.zeros((2, 512), np.float32), np.zeros((2, 512), np.float32), prm1)
        for nm in ('cw', 'cb', 'Wa', 'Wx', 'ba', 'bx', 'lam'):
            P[f'lru_{nm}_{s}'] = c_(d[nm])
    return P


def build_fused(P, L):
    k = K(fused=True)
    X = {nm: k.xin(nm, a.shape) for nm, a in P.items()}
    x = k.xin('x', [L, D])
    mem = k.xin('mem', [256, D])
    out = k.xout('out', [L, D])
    proj0 = k.scratch('proj0', [L, 2064])
    PT0 = k.scratch('PT0', [NF0, L])
    proj1 = k.scratch('proj1', [L, 2816])
    PT1 = k.scratch('PT1', [NF1, L])
    o = k.scratch('o', [L, D])
    odT = k.scratch('odT', [512, L])
    h1 = k.scratch('h1', [L, D])
    h2 = k.scratch('h2', [L, D])
    h3 = k.scratch('h3', [L, D])

    def cblock(l, hin, hout, glu, ob_fm):
        io = dict(oa=o[:, 0:512], hin=hin, wout=X[f'w_out{l}'], g1=X[f'g{l}_1'], ident=X['ident'], hout=h1)
        if ob_fm:
            io['obT'] = odT
        else:
            io['ob'] = o[:, 512:1024]
        if glu:
            io.update(wglu=X['wglu'], bglu=X['bglu'])
        k.begin_phase(f'C1_{l}', io)
        build_C1(L, glu, k=k, ob_fm=ob_fm)
        k.begin_phase(f'C2_{l}', dict(hin=h1, mem=mem, wq=X[f'xa_wq{l}'], wk=X[f'xa_wk{l}'], wv=X[f'xa_wv{l}'], wo=X[f'xa_wo{l}'],
                                      g2=X[f'g{l}_2'], g3=X[f'g{l}_3'], g6=X[f'g{l}_6'], ident=X['ident'], hout=h2))
        build_C2(L, k=k)
        k.begin_phase(f'C3_{l}', dict(hin=h2, w1=X[f'mlp_w1{l}'], w2=X[f'mlp_w2{l}'], g4=X[f'g{l}_4'], g5=X[f'g{l}_5'],
                                      ident=X['ident'], hout=hout))
        build_C3(L, k=k)

    k.begin_phase('A0', dict(x=x, gain=X['g0_0'], W=X['w_in0'], ident=X['ident'], out=proj0, outT=PT0))
    build_A2(L, 2064, FM0, NF0, k=k)
    for s in range(2):
        io_g = dict(qT=PT0[s * 128:(s + 1) * 128, :], kT=PT0[256 + s * 128:256 + (s + 1) * 128, :],
                    ktok=proj0[:, 256 + s * 128:256 + (s + 1) * 128], v=proj0[:, 512 + s * 256:512 + (s + 1) * 256],
                    gate=proj0[:, 1024 + s * 256:1024 + (s + 1) * 256], dlrT=PT0[512:528, :],
                    w2=X[f'gla_w2_{s}'], bdec=X[f'gla_bd_{s}'], gn=X[f'gla_gn_{s}'], triu=X['triu'],
                    trigt=X['trigt'], oa=o[:, s * 256:(s + 1) * 256])
        k.begin_phase(f'GLA{s}', io_g)
        build_GLA(L, k=k)
    for s in range(2):
        io_s = dict(uT=PT0[528 + s * 256:528 + (s + 1) * 256, :], u=proj0[:, 1552 + s * 256:1552 + (s + 1) * 256],
                    triu=X['triu'], iota_p=X['iota_p'], iota_f=X['iota_f'], y=o[:, 512 + s * 256:512 + (s + 1) * 256])
        for nm in ('lam_re', 'lam_im', 'lstep', 'Bre', 'Bim', 'Cre', 'Cim', 'dsk'):
            io_s[nm] = X[f's5_{nm}_{s}']
        k.begin_phase(f'S5{s}', io_s)
        build_S5(L, k=k)
    cblock(0, x, h3, True, False)
    k.begin_phase('A1', dict(x=h3, gain=X['g1_0'], W=X['w_in1'], ident=X['ident'], out=proj1, outT=PT1))
    build_A2(L, 2816, FM1, NF1, k=k)
    io = dict(pr=proj1[:, 0:512], pk=proj1[:, 576:1088], pv=proj1[:, 1088:1600], plw=PT1[0:64, :], pla=PT1[64:128, :],
              plg=PT1[128:256, :], ident=X['ident'], triw=X['rw_triw'], mask5=X['rw_mask5'], rowm=X['rw_rowm'], oc=o[:, 0:512])
    for nm in ('mu1', 'mul', 'w2', 'a2', 'g2', 'vecs'):
        io[nm] = X[f'rw_{nm}']
    k.begin_phase('RW', io)
    build_RWKVP(L, k=k, CH=64)
    streams = []
    for s in range(2):
        io = dict(xbT=PT1[256 + s * 256:256 + (s + 1) * 256, :], gateT=PT1[768 + s * 256:768 + (s + 1) * 256, :],
                  odT=odT[s * 256:(s + 1) * 256, :])
        for nm in ('cw', 'cb', 'Wa', 'Wx', 'ba', 'bx', 'lam'):
            io[nm] = X[f'lru_{nm}_{s}']
        streams.append((f'l{s}_', io, lambda kk: gen_LRU(L, kk)))
    k.begin_phase('LRU', {})
    run_streams(k, streams)
    k.finish()
    cblock(1, h3, out, False, True)
    return k.finish_program()


BATCH, SEQ = 4, 4096
_CACHE = {}


def kernel(**inp):
    inp = {k_: np.asarray(v_) for k_, v_ in inp.items()}
    P = host_params(inp)
    if 'nc' not in _CACHE:
        _CACHE['nc'] = build_fused(P, SEQ)
    nc = _CACHE['nc']
    maps = []
    for b in range(BATCH):
        m = dict(P)
        m['x'] = np.ascontiguousarray(inp['x'][b], dtype=np.float32)
        m['mem'] = np.ascontiguousarray(inp['mem'][b], dtype=np.float32)
        maps.append(m)
    res = run_bass_kernel_spmd(nc, maps, core_ids=list(range(BATCH))).results
    return np.ascontiguousarray(np.stack([res[b]['out'] for b in range(BATCH)]).astype(np.float32))
```

```python
import os
import math
from contextlib import ExitStack


import numpy as np
import concourse.bass as bass
import concourse.mybir as mybir
from concourse.bass_utils import run_bass_kernel_spmd

F32 = mybir.dt.float32
BF16 = mybir.dt.bfloat16
I32 = mybir.dt.int32
AF = mybir.ActivationFunctionType
ALU = mybir.AluOpType
AX = mybir.AxisListType

ENGS = ['pe', 'act', 'dve', 'pool', 'sp']
NDMA_SLOTS = 8
SAME_ENGINE_SYNC = os.environ.get("NOSELF", "0") != "1"


class Prog:
    def __init__(self, nc):
        self.nc = nc
        self.ops = {e: [] for e in ENGS}
        self.cnt = {e: 0 for e in ENGS}
        self.last_w = {}
        self.readers = {}
        self.seen = {e: {} for e in ENGS}
        self.dma_n = {e: 0 for e in ENGS}
        self.dma_tok = {e: [None] * NDMA_SLOTS for e in ENGS}
        self.final_tokens = []
        from contextlib import ExitStack
        self.sem_stack = ExitStack()
        self.sems = {}
        for e in ['pe', 'act', 'dve', 'pool']:
            self.sems[('c', e)] = self.sem_stack.enter_context(nc.semaphore("s_c_" + e))
        for q in ['sp', 'pool']:
            for sl in range(NDMA_SLOTS):
                self.sems[('d', q, sl)] = self.sem_stack.enter_context(nc.semaphore(f"s_d_{q}_{sl}"))

    def barrier(self):
        toks = []
        for e in ['pe', 'act', 'dve', 'pool']:
            if self.cnt[e] > 0:
                toks.append((('c', e), self.cnt[e]))
        for q in ENGS:
            for t in self.dma_tok[q]:
                if t is not None:
                    toks.append(t)
        for e in ENGS:
            waits = []
            for (sem, val) in toks:
                if sem == ('c', e):
                    continue
                if self.seen[e].get(sem, 0) >= val:
                    continue
                waits.append((sem, val))
                self.seen[e][sem] = val
            if waits:
                self.ops[e].append((waits, None, None))
        self.last_w = {}
        self.readers = {}

    def _deps(self, eng, reads, writes):
        toks = []
        for r in reads:
            t = self.last_w.get(r)
            if t is not None:
                toks.append(t)
        for w in writes:
            t = self.last_w.get(w)
            if t is not None:
                toks.append(t)
            toks.extend(self.readers.get(w, []))
        need = {}
        for (sem, val) in toks:
            if not SAME_ENGINE_SYNC and sem == ('c', eng):
                continue
            if sem == ('c', 'pe') and eng == 'pe':
                continue
            if self.seen[eng].get(sem, 0) >= val:
                continue
            if need.get(sem, 0) < val:
                need[sem] = val
        for sem, val in need.items():
            self.seen[eng][sem] = val
        return list(need.items())

    def _commit(self, tok, reads, writes):
        for w in writes:
            self.last_w[w] = tok
            self.readers[w] = []
        for r in reads:
            if r in writes:
                continue
            self.readers.setdefault(r, []).append(tok)

    def op(self, eng, fn, reads=(), writes=()):
        self.nrec = getattr(self, 'nrec', 0) + 1
        if self.nrec > int(os.environ.get("MAXOPS", "100000000")):
            return None
        kp = getattr(self, 'key_prefix', '')
        reads = [r if r.startswith('ps') else kp + r for r in reads]
        writes = [w if w.startswith('ps') else kp + w for w in writes]
        pk = getattr(self, 'ps_prefix', '')
        reads = [('ps' + pk + r[2:]) if r.startswith('ps') else r for r in reads]
        writes = [('ps' + pk + w[2:]) if w.startswith('ps') else w for w in writes]
        writes = list(writes) + [r for r in reads if r.startswith('ps') and r not in writes]
        waits = self._deps(eng, reads, writes)
        self.cnt[eng] += 1
        tok = (('c', eng), self.cnt[eng])
        self.ops[eng].append((waits, fn, tok))
        self._commit(tok, reads, writes)
        return tok

    def dma(self, q, out, in_, reads=(), writes=(), final=False, **kw):
        self.nrec = getattr(self, 'nrec', 0) + 1
        if self.nrec > int(os.environ.get("MAXOPS", "100000000")):
            return None
        kp = getattr(self, 'key_prefix', '')
        reads = [kp + r for r in reads]
        writes = [kp + w for w in writes]
        waits = self._deps(q, reads, writes)
        n = self.dma_n[q]
        slot = n % NDMA_SLOTS
        prev = self.dma_tok[q][slot]
        if prev is not None and self.seen[q].get(prev[0], 0) < prev[1]:
            waits.append(prev)
            self.seen[q][prev[0]] = prev[1]
        tok = (('d', q, slot), 16 * (n // NDMA_SLOTS + 1))
        self.dma_n[q] += 1
        self.dma_tok[q][slot] = tok

        def fn(e, out=out, in_=in_, kw=kw):
            return e.dma_start(out=out, in_=in_, **kw)
        self.ops[q].append((waits, fn, tok))
        self._commit(tok, reads, writes)
        if final:
            self.final_tokens.append(tok)
        return tok

    def emit(self, last=True):
        nc = self.nc
        sems = self.sems
        with nc.Block() as block:
            final = list(self.final_tokens) if last else []

            def run(e, name):
                for waits, fn, tok in self.ops[name]:
                    for (s, v) in waits:
                        e.wait_ge(sems[s], v)
                    if fn is None:
                        continue
                    inst = fn(e)
                    inc = 16 if tok[0][0] == 'd' else 1
                    inst.then_inc(sems[tok[0]], inc)
                if name == 'sp':
                    for (s, v) in final:
                        e.wait_ge(sems[s], v)
                self.ops[name] = []

            @block.tensor
            def _(e):
                run(e, 'pe')

            @block.scalar
            def _(e):
                run(e, 'act')

            @block.vector
            def _(e):
                run(e, 'dve')

            @block.gpsimd
            def _(e):
                run(e, 'pool')

            @block.sync
            def _(e):
                run(e, 'sp')
        if last:
            self.sem_stack.close()


D = 1024
KC = 8
EPS = 1e-6


class K:
    def __init__(self, fused=False):
        self.nc = bass.Bass("TRN2", target_bir_lowering=False)
        self.st = ExitStack()
        self.P = Prog(self.nc)
        self.n = 0
        self.fused = fused
        self.io = {}
        self.pfx = ""

    def begin_phase(self, name, io):
        self.pfx = name + "_"
        self.io = io
        self.st = ExitStack()
        for a in ('wstage', 'rr_cache', 'identf', 'identb'):
            if hasattr(self, a):
                delattr(self, a)

    def scratch(self, name, shape, dt=F32):
        return self.nc.dram_tensor(name, list(shape), dt, kind="Internal").ap()

    def xin(self, name, arr_shape, dt=F32):
        return self.nc.dram_tensor(name, list(arr_shape), dt, kind="ExternalInput").ap()

    def xout(self, name, arr_shape, dt=F32):
        return self.nc.dram_tensor(name, list(arr_shape), dt, kind="ExternalOutput").ap()

    def din(self, name, shape, dt=F32):
        if self.fused:
            ap = self.io[name]
            assert list(ap.shape) == list(shape), (name, ap.shape, shape)
            return ap
        return self.nc.dram_tensor(name, list(shape), dt, kind="ExternalInput").ap()

    def dout(self, name, shape, dt=F32):
        if self.fused:
            ap = self.io[name]
            assert list(ap.shape) == list(shape), (name, ap.shape, shape)
            return ap
        return self.nc.dram_tensor(name, list(shape), dt, kind="ExternalOutput").ap()

    def sb(self, name, shape, dt=F32):
        pers = getattr(self, 'persist', None)
        if pers is not None and (self.pfx + name) in pers:
            return pers[self.pfx + name]
        return self.st.enter_context(self.nc.sbuf_tensor(self.pfx + name, list(shape), dt))

    def push_scope(self, persistent):
        self.persist = getattr(self, 'persist', None) or {}
        for (name, shape, dt) in persistent:
            self.persist[self.pfx + name] = self.st.enter_context(self.nc.sbuf_tensor(self.pfx + name, list(shape), dt))
        self._st_saved = self.st
        self.st = ExitStack()

    def pop_scope(self):
        self.P.barrier()
        self.P.emit(last=False)
        self.st.close()
        self.st = self._st_saved

    def ps(self, name, shape, dt=F32):
        return self.st.enter_context(self.nc.psum_tensor(self.pfx + name, list(shape), dt))

    def finish(self, last=True):
        if self.fused:
            self.P.barrier()
            self.P.emit(last=False)
            self.st.close()
            return None
        self.P.emit()
        self.st.close()
        return self.nc

    def finish_program(self):
        self.P.emit(last=True)
        return self.nc

    def mm(self, out, lhsT, rhs, start, stop, r, w):
        self.P.op('pe', lambda e: e.matmul(out, lhsT=lhsT, rhs=rhs, start=start, stop=stop), reads=r, writes=w)

    def tr(self, out, in_, ident, r, w):
        self.P.op('pe', lambda e: e.transpose(out=out, in_=in_, identity=ident), reads=list(r) + ['ident'], writes=w)

    def act(self, out, in_, func, r, w, **kw):
        self.P.op('act', lambda e: e.activation(out=out, in_=in_, func=func, **kw), reads=r, writes=w)

    def tt(self, eng, out, in0, in1, op, r, w):
        self.P.op(eng, lambda e: e.tensor_tensor(out=out, in0=in0, in1=in1, op=op), reads=r, writes=w)

    def ts(self, eng, out, in0, s1, s2, op0, op1, r, w):
        if op1 is None:
            self.P.op(eng, lambda e: e.tensor_scalar(out=out, in0=in0, scalar1=s1, scalar2=None, op0=op0), reads=r, writes=w)
        else:
            self.P.op(eng, lambda e: e.tensor_scalar(out=out, in0=in0, scalar1=s1, scalar2=s2, op0=op0, op1=op1), reads=r, writes=w)

    def stt(self, out, in0, scalar, in1, op0, op1, r, w):
        self.P.op('dve', lambda e: e.scalar_tensor_tensor(out=out, in0=in0, scalar=scalar, in1=in1, op0=op0, op1=op1),
                  reads=r, writes=w)

    def cp(self, eng, out, in_, r, w):
        if eng == 'act':
            self.P.op('act', lambda e: e.copy(out=out, in_=in_), reads=r, writes=w)
        else:
            self.P.op(eng, lambda e: e.tensor_copy(out=out, in_=in_), reads=r, writes=w)

    def recip(self, out, in_, r, w):
        self.P.op('dve', lambda e: e.reciprocal(out=out, in_=in_), reads=r, writes=w)

    def memset(self, eng, ap, val, w):
        self.P.op(eng, lambda e: e.memset(ap, val), reads=[], writes=w)

    def dma(self, q, out, in_, r=(), w=(), final=False, **kw):
        self.P.dma(q, out, in_, reads=r, writes=w, final=final, **kw)

    def consts(self, ident_d):
        self.identf = self.sb("identf", [128, 128], F32)
        self.identb = self.sb("identb", [128, 128], BF16)
        self.dma('sp', self.identf[:], ident_d, w=['ident'])
        self.cp('dve', self.identb[:], self.identf[:], ['ident'], ['ident'])

    def gain_cols(self, name, g_d):
        t = self.sb(name, [128, KC], F32)
        self.dma('sp', t[:], g_d.rearrange("(kc p) -> p kc", p=128), w=[name], allow_slow_non_contiguous=True)
        return t

    def bcast_row(self, name, vec_d, n):
        t = self.sb(name, [128, n], F32)
        self.dma('sp', t[:], vec_d.partition_broadcast(128), w=[name])
        return t

    def load_weight(self, name, w_d, kchunks, ncols, gcol=None, gkey=None, stage_cols=2048, q='sp'):
        wb = self.sb(name, [128, kchunks, ncols], BF16)
        if not hasattr(self, 'wstage'):
            self.wstage = [self.sb(f"wstage{i}", [128, stage_cols], F32) for i in range(2)]
            self.wstage_n = 0
            self.wstage_cols = stage_cols
        sc = self.wstage_cols
        wv = w_d.rearrange("(kc p) n -> p kc n", p=128)
        for kc in range(kchunks):
            for c0 in range(0, ncols, sc):
                cw = min(sc, ncols - c0)
                b = self.wstage_n % 2
                self.wstage_n += 1
                stg = self.wstage[b]
                self.dma(q, stg[:, 0:cw], wv[:, kc, c0:c0 + cw], w=[f'wstage{b}'])
                eng = 'act' if (kc % 2 == 0) else 'dve'
                if gcol is not None:
                    if eng == 'act':
                        self.act(wb[:, kc, c0:c0 + cw], stg[:, 0:cw], AF.Copy, [f'wstage{b}', gkey], [f'{name}{kc}'],
                                 scale=gcol[:, kc:kc + 1])
                    else:
                        self.ts('dve', wb[:, kc, c0:c0 + cw], stg[:, 0:cw], gcol[:, kc:kc + 1], None, ALU.mult, None,
                                [f'wstage{b}', gkey], [f'{name}{kc}'])
                else:
                    self.cp(eng, wb[:, kc, c0:c0 + cw], stg[:, 0:cw], [f'wstage{b}'], [f'{name}{kc}'])
        return wb

    def rstd_of(self, x_ap, xkey, ss, rstd, junk, key, ncols=D):
        self.act(junk, x_ap, AF.Square, [xkey], ['junk', key + 'ss'], accum_out=ss)
        self.ts('dve', rstd, ss, 1.0 / ncols, EPS, ALU.mult, ALU.add, [key + 'ss'], [key])
        self.act(rstd, rstd, AF.Sqrt, [key], [key])
        self.recip(rstd, rstd, [key], [key])


def pipeline(make_gen, n):
    active = []
    for i in range(n):
        for g in list(active):
            try:
                next(g)
            except StopIteration:
                active.remove(g)
        g = make_gen(i)
        active.append(g)
        try:
            next(g)
        except StopIteration:
            active.remove(g)
    while active:
        for g in list(active):
            try:
                next(g)
            except StopIteration:
                active.remove(g)


def pipeline_gen(make_gen, n):
    active = []
    for i in range(n):
        for g in list(active):
            try:
                next(g)
            except StopIteration:
                active.remove(g)
        g = make_gen(i)
        active.append(g)
        try:
            next(g)
        except StopIteration:
            active.remove(g)
        yield
    while active:
        for g in list(active):
            try:
                next(g)
            except StopIteration:
                active.remove(g)
        yield


def run_streams(k, streams):
    base_pfx = k.pfx
    gens = []
    for (pf, io, gf) in streams:
        gens.append([pf, io, None, gf])
    active = list(gens)
    while active:
        for st in list(active):
            pf, io, g, gf = st
            k.pfx = base_pfx + pf
            k.P.key_prefix = pf
            k.P.ps_prefix = pf
            k.io = io
            try:
                if g is None:
                    st[2] = gf(k)
                    g = st[2]
                next(g)
            except StopIteration:
                active.remove(st)
    k.pfx = base_pfx
    k.P.key_prefix = ''
    k.P.ps_prefix = ''


GELU_C = 1.5957691216057308


def norm_T(k, xt, xkey, xn, xnkey, xT_dst, xTkey, psT, psTkey, ss, rstd, junk, key, evac_eng='act'):
    k.rstd_of(xt, xkey, ss, rstd, junk, key)
    k.ts('dve', xn, xt, rstd, None, ALU.mult, None, [xkey, key], [xnkey])
    for kc in range(KC):
        k.tr(psT[:, kc * 128:(kc + 1) * 128], xn[:, kc * 128:(kc + 1) * 128], k.identb[:], [xnkey], [psTkey])
    k.cp(evac_eng, xT_dst, psT[:].rearrange("p (k t) -> p k t", k=KC), [psTkey], [xTkey])


def post_norm_res(k, ps2, pskeys, ht, hkey, gbc, gkey, tmp2, tmpkeys, ss2, rstd, junk, key):
    for j in range(2):
        k.act(junk[:, 0:512], ps2[j], AF.Square, [pskeys[j]], ['junk', key + f'ss{j}'], accum_out=ss2[:, j:j + 1])
    k.tt('dve', ss2[:, 0:1], ss2[:, 0:1], ss2[:, 1:2], ALU.add, [key + 'ss0', key + 'ss1'], [key + 'ss0'])
    k.ts('dve', rstd, ss2[:, 0:1], 1.0 / D, EPS, ALU.mult, ALU.add, [key + 'ss0'], [key])
    k.act(rstd, rstd, AF.Sqrt, [key], [key])
    k.recip(rstd, rstd, [key], [key])
    for j in range(2):
        sl = slice(j * 512, (j + 1) * 512)
        k.stt(tmp2[j], ps2[j], rstd, gbc[:, sl], ALU.mult, ALU.mult, [pskeys[j], key, gkey], [tmpkeys[j]])
        k.tt('pool', ht[:, sl], ht[:, sl], tmp2[j], ALU.add, [tmpkeys[j], hkey], [hkey])


def build_C1(NTOK, glu, k=None, ob_fm=False):
    k = k or K()
    NT = NTOK // 128
    oa = k.din("oa", [NTOK, 512])
    if ob_fm:
        obT = k.din("obT", [512, NTOK])
    else:
        ob = k.din("ob", [NTOK, 512])
    hin = k.din("hin", [NTOK, D])
    wout = k.din("wout", [D, D])
    g1 = k.din("g1", [D])
    ident_d = k.din("ident", [128, 128])
    if glu:
        wglu = k.din("wglu", [512, 512])
        bglu = k.din("bglu", [512])
    hout = k.dout("hout", [NTOK, D])
    k.consts(ident_d)
    g1bc = k.bcast_row("g1bc", g1, D)
    Wout = k.load_weight("Wout", wout, KC, D, stage_cols=1024)
    if glu:
        Wglu = k.load_weight("Wglu", wglu, 4, 512)
        bgbc = k.bcast_row("bgbc", bglu, 512)

    def ring(nm, shape, n, dt=F32):
        return [k.sb(f"{nm}{j}", shape, dt) for j in range(n)]
    oc = ring("oc", [128, D], 10 if glu else 4)
    ocb = ring("ocb", [128, D], 3, BF16)
    oT = ring("oT", [128, KC, 128], 3, BF16)
    ht = ring("ht", [128, D], 4)
    mix = ring("mix", [128, D], 5)
    tmp = ring("tmp", [128, D], 3)
    ss2 = ring("ss2", [128, 2], 4)
    rstd = ring("rstd", [128, 1], 5)
    junk = k.sb("junk", [128, D], BF16)
    if ob_fm:
        obt = ring("obt", [128, 4, 128], 4)
    if glu:
        yb = ring("yb", [128, 512], 3, BF16)
        yT = ring("yT", [128, 4, 128], 3, BF16)
        t1 = ring("t1", [128, 512], 9)
        zs = ring("zs", [128, 512], 4)
        psTg = k.ps("psTg", [128, D], BF16)
        psG = k.ps("psG", [128, 512])
    psTm = [k.ps(f"psTm{j}", [128, D], BF16) for j in range(2)]
    psM = [k.ps(f"psM{j}", [128, 512]) for j in range(4)]

    def tile(i):
        rows = slice(i * 128, (i + 1) * 128)
        def T(lst, nm):
            j = i % len(lst)
            return lst[j], f'{nm}{j}'
        oc_, koc = T(oc, 'oc'); ocb_, kocb = T(ocb, 'ocb'); oT_, koT = T(oT, 'oT'); ht_, kht = T(ht, 'ht')
        mix_, kmix = T(mix, 'mix'); tmp_, ktmp = T(tmp, 'tmp'); ss_, kss = T(ss2, 'ss2'); rs_, krs = T(rstd, 'rstd')
        pm = [psM[2 * (i % 2)], psM[2 * (i % 2) + 1]]
        kpm = [f'psM{2 * (i % 2)}', f'psM{2 * (i % 2) + 1}']
        ptm, kptm = psTm[i % 2], f'psTm{i % 2}'
        kA, kB = koc + 'A', koc + 'B'
        k.dma('sp', oc_[:, 0:512], oa[rows, :], w=[kA])
        if ob_fm:
            obt_, kobt = T(obt, 'obt')
            k.dma('sp', obt_[:], obT[:, rows].rearrange("(a p) t -> p a t", p=128), w=[kobt])
        else:
            k.dma('sp', oc_[:, 512:1024], ob[rows, :], w=[kB])
        yield
        if glu:
            y = oc_[:, 512:1024]
            yb_, kyb = T(yb, 'yb'); yT_, kyT = T(yT, 'yT'); t1_, kt1 = T(t1, 't1'); zs_, kzs = T(zs, 'zs')
            k.cp('dve', yb_[:], y, [kB], [kyb])
            k.act(t1_[:], y, AF.Square, [kB], [kt1])
            k.act(t1_[:], t1_[:], AF.Copy, [kt1], [kt1], scale=0.044715, bias=1.0)
            yield
            for kc in range(4):
                k.tr(psTg[:, kc * 128:(kc + 1) * 128], yb_[:, kc * 128:(kc + 1) * 128], k.identb[:], [kyb], ['psTg'])
            k.tt('pool', t1_[:], t1_[:], y, ALU.mult, [kt1, kB], [kt1])
            yield
            k.cp('act', yT_[:], psTg[:, 0:512].rearrange("p (k t) -> p k t", k=4), ['psTg'], [kyT])
            k.act(t1_[:], t1_[:], AF.Sigmoid, [kt1], [kt1], scale=GELU_C)
            yield
            for kc in range(4):
                k.mm(psG[:], yT_[:, kc, :], Wglu[:, kc, :], kc == 0, kc == 3, [kyT, f'Wglu{kc}'], ['psG'])
            yield
            k.tt('dve', zs_[:], psG[:], bgbc[:], ALU.add, ['psG', 'bgbc'], [kzs])
            yield
            k.act(zs_[:], zs_[:], AF.Sigmoid, [kzs], [kzs])
            yield
            k.tt('dve', zs_[:], t1_[:], zs_[:], ALU.mult, [kt1, kzs], [kzs])
            k.tt('dve', y, y, zs_[:], ALU.mult, [kB, kzs], [kB])
        if ob_fm:
            k.cp('dve', ocb_[:, 0:512], oc_[:, 0:512], [kA], [kocb])
            k.cp('pool', oT_[:, 4:8, :], obt_[:], [kobt], [koT + 'b'])
        else:
            k.cp('dve', ocb_[:], oc_[:], [kA, kB], [kocb])
        yield
        nk = 4 if ob_fm else KC
        for kc in range(nk):
            k.tr(ptm[:, kc * 128:(kc + 1) * 128], ocb_[:, kc * 128:(kc + 1) * 128], k.identb[:], [kocb], [kptm])
        yield
        k.cp('act', oT_[:, 0:nk, :], ptm[:, 0:nk * 128].rearrange("p (k t) -> p k t", k=nk), [kptm], [koT])
        yield
        for cg in range(2):
            for kc in range(KC):
                ok_ = (koT + 'b') if (ob_fm and kc >= 4) else koT
                k.mm(pm[cg][:], oT_[:, kc, :], Wout[:, kc, cg * 512:(cg + 1) * 512], kc == 0, kc == KC - 1,
                     [ok_, f'Wout{kc}'], [kpm[cg]])
        yield
        for j in range(2):
            k.act(junk[:, 0:512], pm[j][:], AF.Square, [kpm[j]], ['junk', kss], accum_out=ss_[:, j:j + 1])
        for j in range(2):
            k.cp('act', mix_[:, j * 512:(j + 1) * 512], pm[j][:], [kpm[j]], [kmix])
        k.dma('sp', ht_[:], hin[rows, :], w=[kht])
        yield
        k.tt('dve', ss_[:, 0:1], ss_[:, 0:1], ss_[:, 1:2], ALU.add, [kss], [kss])
        k.ts('dve', rs_[:], ss_[:, 0:1], 1.0 / D, EPS, ALU.mult, ALU.add, [kss], [krs])
        yield
        k.act(rs_[:], rs_[:], AF.Sqrt, [krs], [krs])
        yield
        k.recip(rs_[:], rs_[:], [krs], [krs])
        k.stt(tmp_[:], mix_[:], rs_[:], g1bc[:], ALU.mult, ALU.mult, [kmix, krs, 'g1bc'], [ktmp])
        yield
        k.tt('pool', ht_[:], ht_[:], tmp_[:], ALU.add, [kht, ktmp], [kht])
        k.dma('pool', hout[rows, :], ht_[:], r=[kht], final=True)

    pipeline(tile, NT)
    return k.finish()


def build_C3(NTOK, k=None):
    k = k or K()
    NB = NTOK // 512
    DFF = 4096
    FC = DFF // 128
    hin = k.din("hin", [NTOK, D])
    w1 = k.din("w1", [D, DFF])
    w2 = k.din("w2", [DFF, D])
    g4 = k.din("g4", [D])
    g5 = k.din("g5", [D])
    ident_d = k.din("ident", [128, 128])
    hout = k.dout("hout", [NTOK, D])
    k.consts(ident_d)
    g4c = k.gain_cols("g4c", g4)
    g5bc = k.bcast_row("g5bc", g5, D)
    W1 = k.load_weight("W1", w1, KC, DFF, gcol=g4c, gkey='g4c', stage_cols=512)
    W2 = k.load_weight("W2", w2, FC, D, stage_cols=512)
    ht = [k.sb(f"ht{i}", [128, D]) for i in range(4)]
    xn = [k.sb(f"xn{i}", [128, D], BF16) for i in range(2)]
    xT = k.sb("xT", [128, KC, 512], BF16)
    AT = k.sb("AT", [128, FC, 512], BF16)
    sq = [k.sb(f"sq{i}", [128, 512]) for i in range(2)]
    junk = k.sb("junk", [128, D], BF16)
    ss = [k.sb(f"ss{i}", [128, 1]) for i in range(2)]
    ss2 = [k.sb(f"ss2{i}", [128, 2]) for i in range(2)]
    rstd = [k.sb(f"rstd{i}", [128, 1]) for i in range(2)]
    rstd2 = [k.sb(f"rstdb{i}", [128, 1]) for i in range(2)]
    ss4 = k.sb("ss4", [128, 4])
    rs4 = k.sb("rs4", [128, 4])
    psT = k.ps("psT", [128, D], BF16)
    psU = [k.ps(f"psU{i}", [128, 512]) for i in range(3)]
    psD = [k.ps(f"psD{i}", [128, 512]) for i in range(4)]
    nu = 0
    for blk in range(NB):
        for tt in range(4):
            i = blk * 4 + tt
            k.dma('sp', ht[tt][:], hin[i * 128:(i + 1) * 128, :], w=[f'ht{tt}'])
        for tt in range(4):
            k.act(junk[:], ht[tt][:], AF.Square, [f'ht{tt}'], ['junk', f'nss{tt}'], accum_out=ss4[:, tt:tt + 1])
        k.ts('dve', rs4[:], ss4[:], 1.0 / D, EPS, ALU.mult, ALU.add, [f'nss{t_}' for t_ in range(4)], ['rs4'])
        k.act(rs4[:], rs4[:], AF.Sqrt, ['rs4'], ['rs4'])
        k.recip(rs4[:], rs4[:], ['rs4'], ['rs4'])
        for tt in range(4):
            b = tt % 2
            k.ts('dve', xn[b][:], ht[tt][:], rs4[:, tt:tt + 1], None, ALU.mult, None, [f'ht{tt}', 'rs4'], [f'xn{b}'])
            for kc in range(KC):
                k.tr(psT[:, kc * 128:(kc + 1) * 128], xn[b][:, kc * 128:(kc + 1) * 128], k.identb[:], [f'xn{b}'], ['psT'])
            k.cp('act', xT[:, :, tt * 128:(tt + 1) * 128], psT[:].rearrange("p (k t) -> p k t", k=KC), ['psT'], ['xT'])
        for fc in range(FC):
            pu = nu % 3
            nu += 1
            for kc in range(KC):
                k.mm(psU[pu][:], W1[:, kc, fc * 128:(fc + 1) * 128], xT[:, kc, :], kc == 0, kc == KC - 1,
                     [f'W1{kc}', 'xT'], [f'psU{pu}'])
            sb_ = fc % 2
            k.act(sq[sb_][:], psU[pu][:], AF.Square, [f'psU{pu}'], [f'sq{sb_}'])
            k.stt(AT[:, fc, :], psU[pu][:], 0.0, sq[sb_][:], ALU.is_gt, ALU.mult, [f'psU{pu}', f'sq{sb_}'], ['AT'])
        for tt in range(4):
            i = blk * 4 + tt
            b = i % 2
            rows = slice(i * 128, (i + 1) * 128)
            for cg in range(2):
                pd = 2 * b + cg
                for fc in range(FC):
                    k.mm(psD[pd][:], AT[:, fc, tt * 128:(tt + 1) * 128], W2[:, fc, cg * 512:(cg + 1) * 512],
                         fc == 0, fc == FC - 1, ['AT', f'W2{fc}'], [f'psD{pd}'])
            post_norm_res(k, [psD[2 * b][:], psD[2 * b + 1][:]], [f'psD{2 * b}', f'psD{2 * b + 1}'], ht[tt], f'ht{tt}',
                          g5bc, 'g5bc', [sq[0][:], sq[1][:]], ['sq0', 'sq1'], ss2[b], rstd2[b][:], junk, f'pn{b}')
            k.dma('pool', hout[rows, :], ht[tt][:], r=[f'ht{tt}'], final=True)
    return k.finish()


def build_C2(NTOK, k=None):
    k = k or K()
    NB = NTOK // 512
    MEM = 256
    hin = k.din("hin", [NTOK, D])
    mem = k.din("mem", [MEM, D])
    wq = k.din("wq", [D, D])
    wk = k.din("wk", [D, D])
    wv = k.din("wv", [D, D])
    wo = k.din("wo", [D, D])
    g2 = k.din("g2", [D])
    g3 = k.din("g3", [D])
    g6 = k.din("g6", [D])
    ident_d = k.din("ident", [128, 128])
    hout = k.dout("hout", [NTOK, D])
    k.consts(ident_d)
    g2c = k.gain_cols("g2c", g2)
    g6c = k.gain_cols("g6c", g6)
    g3bc = k.bcast_row("g3bc", g3, D)
    Wk = k.load_weight("Wk", wk, KC, D, gcol=g6c, gkey='g6c', stage_cols=1024)
    Wv = k.load_weight("Wv", wv, KC, D, gcol=g6c, gkey='g6c', stage_cols=1024)
    Wq = k.load_weight("Wq", wq, KC, D, gcol=g2c, gkey='g2c', stage_cols=1024)
    Wo = k.load_weight("Wo", wo, KC, D, stage_cols=1024)
    ht = [k.sb(f"ht{i}", [128, D]) for i in range(2)]
    xn = [k.sb(f"xn{i}", [128, D], BF16) for i in range(2)]
    xT = [k.sb(f"xT{i}", [128, KC, 512], BF16) for i in range(2)]
    memT = k.sb("memT", [128, KC, MEM], BF16)
    KT = k.sb("KT", [128, KC, MEM], BF16)
    V = k.sb("V", [128, 2, D], BF16)
    QT = [k.sb(f"QT{i}", [128, KC, 512], BF16) for i in range(2)]
    Pm = [k.sb(f"Pm{i}", [128, 4, MEM], BF16) for i in range(3)]
    Pn = [k.sb(f"Pn{i}", [128, 4, MEM], BF16) for i in range(3)]
    PT = [k.sb(f"PT{i}", [128, 8, 128], BF16) for i in range(3)]
    OT = [k.sb(f"OT{i}", [128, KC, 128], BF16) for i in range(3)]
    tmp = [k.sb(f"tmp{i}", [128, 512]) for i in range(2)]
    junk = k.sb("junk", [128, D], BF16)
    ss = [k.sb(f"ss{i}", [128, 1]) for i in range(2)]
    ss2 = [k.sb(f"ss2{i}", [128, 2]) for i in range(2)]
    rstd = [k.sb(f"rstd{i}", [128, 1]) for i in range(2)]
    rstd2 = [k.sb(f"rstdb{i}", [128, 1]) for i in range(2)]
    mx = [k.sb(f"mx{i}", [128, 4]) for i in range(3)]
    sm = [k.sb(f"sm{i}", [128, 4]) for i in range(3)]
    psT = k.ps("psT", [128, D], BF16)
    psA = k.ps("psA", [128, 1024])
    psS = k.ps("psS", [128, 1024])
    psX = k.ps("psX", [128, 1024])
    for mt in range(2):
        k.dma('sp', ht[mt][:], mem[mt * 128:(mt + 1) * 128, :], w=[f'ht{mt}'])
        norm_T(k, ht[mt][:], f'ht{mt}', xn[mt][:], f'xn{mt}', memT[:, :, mt * 128:(mt + 1) * 128], 'memT', psT[:], 'psT',
               ss[mt][:], rstd[mt][:], junk[:], f'n{mt}')
    for cc in range(KC):
        pa = cc % 2
        for kc in range(KC):
            k.mm(psA[:, pa * 512:pa * 512 + MEM], Wk[:, kc, cc * 128:(cc + 1) * 128], memT[:, kc, :], kc == 0, kc == KC - 1,
                 [f'Wk{kc}', 'memT'], [f'psA{pa}'])
        k.cp('act' if cc % 2 else 'dve', KT[:, cc, :], psA[:, pa * 512:pa * 512 + MEM], [f'psA{pa}'], [f'KT{cc}'])
    for mt in range(2):
        for cg in range(2):
            for kc in range(KC):
                k.mm(psX[:, cg * 512:(cg + 1) * 512], memT[:, kc, mt * 128:(mt + 1) * 128], Wv[:, kc, cg * 512:(cg + 1) * 512],
                     kc == 0, kc == KC - 1, ['memT', f'Wv{kc}'], [f'psX{cg}'])
            k.cp('act' if cg else 'dve', V[:, mt, cg * 512:(cg + 1) * 512], psX[:, cg * 512:(cg + 1) * 512], [f'psX{cg}'], [f'V{mt}{cg}'])
    xt6 = [k.sb(f"xt6_{i}", [128, D]) for i in range(6)]
    ss6 = [k.sb(f"ss6_{i}", [128, 1]) for i in range(4)]
    rs6 = [k.sb(f"rs6_{i}", [128, 1]) for i in range(5)]
    xn3 = [k.sb(f"xn3_{i}", [128, D], BF16) for i in range(3)]
    psTx = k.ps("psTx", [128, D], BF16)

    def tile(i):
        blk, tt = divmod(i, 4)
        xb = blk % 2
        b = i % 3
        rows = slice(i * 128, (i + 1) * 128)
        tsl = slice(tt * 128, (tt + 1) * 128)
        def T(lst, nm):
            j = i % len(lst)
            return lst[j], f'{nm}{j}'
        xt_, kxt = T(xt6, 'xt6'); ss_, kss = T(ss6, 'ss6'); rs_, krs = T(rs6, 'rs6'); xn_, kxn = T(xn3, 'xn3')
        hb = i % 2
        k.dma('sp', xt_[:], hin[rows, :], w=[kxt])
        yield
        k.act(junk[:], xt_[:], AF.Square, [kxt], ['junk', kss], accum_out=ss_[:])
        yield
        k.ts('dve', rs_[:], ss_[:], 1.0 / D, EPS, ALU.mult, ALU.add, [kss], [krs])
        yield
        k.act(rs_[:], rs_[:], AF.Sqrt, [krs], [krs])
        yield
        k.recip(rs_[:], rs_[:], [krs], [krs])
        k.ts('dve', xn_[:], xt_[:], rs_[:], None, ALU.mult, None, [kxt, krs], [kxn])
        yield
        for kc in range(KC):
            k.tr(psTx[:, kc * 128:(kc + 1) * 128], xn_[:, kc * 128:(kc + 1) * 128], k.identb[:], [kxn], ['psTx'])
        yield
        k.cp('act', xT[xb][:, :, tsl], psTx[:].rearrange("p (k t) -> p k t", k=KC), ['psTx'], [f'xT{xb}'])
        yield
        if tt == 3:
            for cc in range(KC):
                pa = cc % 2
                for kc in range(KC):
                    k.mm(psA[:, pa * 512:(pa + 1) * 512], Wq[:, kc, cc * 128:(cc + 1) * 128], xT[xb][:, kc, :], kc == 0, kc == KC - 1,
                         [f'Wq{kc}', f'xT{xb}'], [f'psA{pa}'])
                k.cp('act' if cc % 2 else 'dve', QT[xb][:, cc, :], psA[:, pa * 512:(pa + 1) * 512], [f'psA{pa}'], [f'QT{xb}{cc}'])
        yield
        yield
        yield
        yield
        for h in range(4):
            sb_ = h // 2
            for j in range(2):
                cc = 2 * h + j
                k.mm(psS[:, h * MEM:(h + 1) * MEM], QT[xb][:, cc, tsl], KT[:, cc, :], j == 0, j == 1,
                     [f'QT{xb}{cc}', f'KT{cc}'], [f'psS{sb_}'])
        k.P.op('dve', lambda e, b=b: e.tensor_reduce(out=mx[b][:], in_=psS[:].rearrange("p (h m) -> p h m", h=4),
                                                    axis=AX.X, op=ALU.max),
               reads=['psS0', 'psS1'], writes=[f'mx{b}'])
        k.ts('dve', mx[b][:], mx[b][:], -1.0 / 16.0, None, ALU.mult, None, [f'mx{b}'], [f'mx{b}'])
        for h in range(4):
            k.act(Pm[b][:, h, :], psS[:, h * MEM:(h + 1) * MEM], AF.Exp, [f'psS{h // 2}', f'mx{b}'], [f'Pm{b}', f'sm{b}'],
                  scale=1.0 / 16.0, bias=mx[b][:, h:h + 1], accum_out=sm[b][:, h:h + 1])
        k.recip(sm[b][:], sm[b][:], [f'sm{b}'], [f'sm{b}'])
        k.tt('dve', Pn[b][:], Pm[b][:], sm[b][:].unsqueeze(2).broadcast_to([128, 4, MEM]), ALU.mult,
             [f'Pm{b}', f'sm{b}'], [f'Pn{b}'])
        yield
        for h in range(4):
            for mt in range(2):
                k.tr(psT[:, (h * 2 + mt) * 128:(h * 2 + mt + 1) * 128], Pn[b][:, h, mt * 128:(mt + 1) * 128], k.identb[:],
                     [f'Pn{b}'], ['psT'])
        k.cp('act', PT[b][:], psT[:].rearrange("p (k t) -> p k t", k=8), ['psT'], [f'PT{b}'])
        for cc in range(KC):
            h = cc // 2
            pa = cc // 4
            for mt in range(2):
                k.mm(psA[:, cc * 128:(cc + 1) * 128], V[:, mt, cc * 128:(cc + 1) * 128], PT[b][:, h * 2 + mt, :],
                     mt == 0, mt == 1, [f'V{mt}{cc // 4}', f'PT{b}'], [f'psA{pa}'])
        k.cp('dve', OT[b][:, 0:4, :], psA[:, 0:512].rearrange("p (k t) -> p k t", k=4), ['psA0'], [f'OT{b}_0'])
        k.cp('act', OT[b][:, 4:8, :], psA[:, 512:1024].rearrange("p (k t) -> p k t", k=4), ['psA1'], [f'OT{b}_1'])
        k.dma('sp', ht[hb][:], hin[rows, :], w=[f'ht{hb}'])
        yield
        for cg in range(2):
            for cc in range(KC):
                k.mm(psX[:, cg * 512:(cg + 1) * 512], OT[b][:, cc, :], Wo[:, cc, cg * 512:(cg + 1) * 512],
                     cc == 0, cc == KC - 1, [f'OT{b}_{cc // 4}', f'Wo{cc}'], [f'psX{cg}'])
        post_norm_res(k, [psX[:, 0:512], psX[:, 512:1024]], ['psX0', 'psX1'], ht[hb], f'ht{hb}',
                      g3bc, 'g3bc', [tmp[0][:], tmp[1][:]], ['tmp0', 'tmp1'], ss2[b % 2], rstd2[b % 2][:], junk, f'pn{b % 2}')
        k.dma('pool', hout[rows, :], ht[hb][:], r=[f'ht{hb}'], final=True)

    pipeline(tile, NTOK // 128)
    return k.finish()


def build_A2(NTOK, NC, fm, NF, k=None):
    k = k or K()
    NB = NTOK // 512
    x = k.din("x", [NTOK, D])
    gain = k.din("gain", [D])
    W = k.din("W", [D, NC])
    ident_d = k.din("ident", [128, 128])
    out = k.dout("out", [NTOK, NC])
    outT = k.dout("outT", [NF, NTOK])
    k.consts(ident_d)
    gc = k.gain_cols("gc", gain)
    Wb = k.load_weight("Wb", W, KC, NC, gcol=gc, gkey='gc', stage_cols=1408)
    cgs = [(c0, min(512, NC - c0)) for c0 in range(0, NC, 512)]
    def ring(nm, shape, n, dt=F32):
        return [k.sb(f"{nm}{j}", shape, dt) for j in range(n)]
    xt = ring("xt", [128, D], 6)
    xn = ring("xn", [128, D], 3, BF16)
    xT = [k.sb(f"xT{i}", [128, KC, 512], BF16) for i in range(2)]
    ot = [k.sb(f"ot{i}", [128, NC]) for i in range(2)]
    ft = [k.sb(f"ft{i}", [128, 512]) for i in range(2)]
    junk = k.sb("junk", [128, D], BF16)
    ss = ring("ss", [128, 1], 4)
    rstd = ring("rstd", [128, 1], 5)
    psT = k.ps("psT", [128, D], BF16)
    psO = [k.ps(f"psO{i}", [128, 512]) for i in range(4)]
    psF = [k.ps(f"psF{i}", [128, 512]) for i in range(2)]
    cnt = {'no': 0, 'nf': 0}

    def tile(i):
        blk, tt = divmod(i, 4)
        xb = blk % 2
        def T(lst, nm):
            j = i % len(lst)
            return lst[j], f'{nm}{j}'
        xt_, kxt = T(xt, 'xt'); xn_, kxn = T(xn, 'xn'); ss_, kss = T(ss, 'ss'); rs_, krs = T(rstd, 'rstd')
        k.dma('sp', xt_[:], x[i * 128:(i + 1) * 128, :], w=[kxt])
        yield
        k.act(junk[:], xt_[:], AF.Square, [kxt], ['junk', kss], accum_out=ss_[:])
        yield
        k.ts('dve', rs_[:], ss_[:], 1.0 / D, EPS, ALU.mult, ALU.add, [kss], [krs])
        yield
        k.act(rs_[:], rs_[:], AF.Sqrt, [krs], [krs])
        yield
        k.recip(rs_[:], rs_[:], [krs], [krs])
        k.ts('dve', xn_[:], xt_[:], rs_[:], None, ALU.mult, None, [kxt, krs], [kxn])
        yield
        for kc in range(KC):
            k.tr(psT[:, kc * 128:(kc + 1) * 128], xn_[:, kc * 128:(kc + 1) * 128], k.identb[:], [kxn], ['psT'])
        yield
        k.cp('act', xT[xb][:, :, tt * 128:(tt + 1) * 128], psT[:].rearrange("p (k t) -> p k t", k=KC), ['psT'], [f'xT{xb}'])
        yield
        if tt != 3:
            return
        for t2 in range(4):
            i2 = blk * 4 + t2
            b = i2 % 2
            for ci, (c0, cw) in enumerate(cgs):
                pb = cnt['no'] % 4
                cnt['no'] += 1
                for kc in range(KC):
                    k.mm(psO[pb][:, 0:cw], xT[xb][:, kc, t2 * 128:(t2 + 1) * 128], Wb[:, kc, c0:c0 + cw], kc == 0, kc == KC - 1,
                         [f'xT{xb}', f'Wb{kc}'], [f'psO{pb}'])
                k.cp('dve' if pb % 2 == 0 else 'act', ot[b][:, c0:c0 + cw], psO[pb][:, 0:cw], [f'psO{pb}'], [f'ot{b}_{pb % 2}'])
            k.dma('pool', out[i2 * 128:(i2 + 1) * 128, :], ot[b][:], r=[f'ot{b}_0', f'ot{b}_1'], final=True)
        for (c0, cw, r0) in fm:
            pf = cnt['nf'] % 2
            cnt['nf'] += 1
            for kc in range(KC):
                k.mm(psF[pf][0:cw, :], Wb[:, kc, c0:c0 + cw], xT[xb][:, kc, :], kc == 0, kc == KC - 1,
                     [f'Wb{kc}', f'xT{xb}'], [f'psF{pf}'])
            k.cp('dve' if pf == 0 else 'act', ft[pf][0:cw, :], psF[pf][0:cw, :], [f'psF{pf}'], [f'ft{pf}'])
            k.dma('pool', outT[r0:r0 + cw, blk * 512:(blk + 1) * 512], ft[pf][0:cw, :], r=[f'ft{pf}'], final=True)

    pipeline(tile, NTOK // 128)
    return k.finish()


def gen_GLA(L, k):
    NT = L // 128
    qT = k.din("qT", [128, L])
    kT = k.din("kT", [128, L])
    ktok = k.din("ktok", [L, 128])
    v = k.din("v", [L, 256])
    gate = k.din("gate", [L, 256])
    dlrT = k.din("dlrT", [16, L])
    w2 = k.din("w2", [16, 128])
    bdec = k.din("bdec", [1, 128])
    gn = k.din("gn", [256])
    triu_d = k.din("triu", [128, 128])
    trigt_d = k.din("trigt", [128, 128])
    oa = k.dout("oa", [L, 256])

    triu = k.sb("triu_s", [128, 128])
    trigt = k.sb("trigt_s", [128, 128])
    k.dma('sp', triu[:], triu_d, w=['triu'])
    k.dma('sp', trigt[:], trigt_d, w=['trigt'])
    w2s = k.sb("w2s", [16, 128])
    k.dma('sp', w2s[:], w2, w=['w2s'])
    bds = k.sb("bds", [1, 128])
    k.dma('sp', bds[:], bdec, w=['bds'])
    ones1 = k.sb("ones1", [1, 128])
    k.memset('dve', ones1[:], 1.0, ['ones1'])
    gnbc = k.bcast_row("gnbc", gn, 256)
    S = k.sb("S", [128, 128], mybir.dt.float32r)
    zS = k.sb("zS", [128, 128])
    k.memset('dve', zS[:], 0.0, ['zS'])
    k.cp('dve', S[:], zS[:], ['zS'], ['S'])
    rm = k.sb("rm", [128, 2])
    k.memset('dve', rm[:], 0.0, ['rm'])
    k.memset('dve', rm[0:64, 0:1], 0.125, ['rm'])
    k.memset('dve', rm[64:128, 1:2], 0.125, ['rm'])

    def ring(nm, shape, n, dt=F32):
        return [k.sb(f"{nm}{j}", shape, dt) for j in range(n)]
    FR_ = mybir.dt.float32r
    triur = k.sb("triur", [128, 128], FR_)
    trigtr = k.sb("trigtr", [128, 128], FR_)
    k.cp('dve', triur[:], triu[:], ['triu'], ['triur'])
    k.cp('dve', trigtr[:], trigt[:], ['trigt'], ['trigtr'])
    vr = ring("vr", [128, 256], 10, FR_)
    qTt, kTt, kt, gt = ring("qTt", [128, 128], 8), ring("kTt", [128, 128], 8), ring("kt", [128, 128], 8), ring("gt", [128, 256], 8)
    vt = ring("vt", [128, 256], 11)
    dt_ = ring("dt", [16, 128], 3)
    la = ring("la", [128, 128], 4, mybir.dt.float32r)
    sg = ring("sg", [128, 256], 16)
    EqT, EkT, Eks = ring("EqT", [128, 128], 7), ring("EkT", [128, 128], 3), ring("Eks", [128, 128], 3)
    qin, kin, kst = ring("qin", [128, 2, 128], 5, mybir.dt.float32r), ring("kin", [128, 128], 3, mybir.dt.float32r), ring("kst", [128, 128], 5, mybir.dt.float32r)
    sc0, sc1 = ring("sc0_", [128, 128], 3, mybir.dt.float32r), ring("sc1_", [128, 128], 3, mybir.dt.float32r)
    osr = ring("osr", [128, 256], 6)
    osb = ring("osb", [128, 256], 3)
    ss, rs = ring("ss", [128, 2], 4), ring("rs", [128, 2], 5)
    ot = ring("ot", [128, 256], 3)
    junk = k.sb("junk", [128, 128])
    psZ = [k.ps(f"psZ{j}", [128, 512]) for j in range(2)]
    psA = [k.ps(f"psA{j}", [128, 512]) for j in range(2)]
    psB = [k.ps(f"psB{j}", [128, 512]) for j in range(2)]
    psC = [k.ps(f"psC{j}", [128, 512]) for j in range(2)]

    def tile(i):
        rows = slice(i * 128, (i + 1) * 128)
        R = lambda lst: (lst[i % len(lst)], f'{lst[0].name if hasattr(lst[0], "name") else id(lst)}_{i % len(lst)}')
        def T(lst, nm):
            j = i % len(lst)
            return lst[j], f'{nm}{j}'
        q_, kq = T(qTt, 'qTt'); kT_, kkT = T(kTt, 'kTt'); kt_, kkt = T(kt, 'kt'); v_, kv = T(vt, 'vt'); g_, kg = T(gt, 'gt')
        d_, kd = T(dt_, 'dt'); la_, kla = T(la, 'la'); sg_, ksg = T(sg, 'sg')
        Eq, kEq = T(EqT, 'EqT'); Ek, kEk = T(EkT, 'EkT'); Es, kEs = T(Eks, 'Eks')
        qi, kqi = T(qin, 'qin'); ki, kki = T(kin, 'kin'); ks, kks = T(kst, 'kst')
        scs = [T(sc0, 'sc0_'), T(sc1, 'sc1_')]
        orw, korw = T(osr, 'osr'); ob_, kob = T(osb, 'osb'); ss_, kss = T(ss, 'ss'); rs_, krs = T(rs, 'rs'); ot_, kot = T(ot, 'ot')
        pz, kpz = psZ[i % 2], f'psZ{i % 2}'
        pa, kpa = psA[i % 2], f'psA{i % 2}'
        pb, kpb = psB[i % 2], f'psB{i % 2}'
        pc, kpc = psC[i % 2], f'psC{i % 2}'
        k.dma('sp', q_[:], qT[:, rows], w=[kq])
        k.dma('sp', kT_[:], kT[:, rows], w=[kkT])
        k.dma('sp', kt_[:], ktok[rows, :], w=[kkt])
        k.dma('sp', v_[:], v[rows, :], w=[kv])
        k.dma('sp', g_[:], gate[rows, :], w=[kg])
        k.dma('sp', d_[:], dlrT[:, rows], w=[kd])
        yield
        k.mm(pz[:, 0:128], d_[:], w2s[:], True, False, [kd, 'w2s'], [kpz])
        k.mm(pz[:, 0:128], ones1[:], bds[:], False, True, ['ones1', 'bds'], [kpz])
        yield
        k.act(la_[:], pz[:, 0:128], AF.Exp, [kpz], [kla], scale=-1.0)
        k.act(la_[:], la_[:].bitcast(F32), AF.Ln, [kla], [kla], bias=1.0)
        k.act(sg_[:], g_[:], AF.Exp, [kg], [ksg], scale=-1.0)
        vr_, kvr = T(vr, 'vr')
        k.cp('act', vr_[:], v_[:], [kv], [kvr])
        yield
        k.ts('dve', la_[:], la_[:].bitcast(F32), -1.0 / 16.0, None, ALU.mult, None, [kla], [kla])
        k.ts('dve', sg_[:], sg_[:], 1.0, None, ALU.add, None, [ksg], [ksg])
        k.recip(sg_[:], sg_[:], [ksg], [ksg])
        yield
        k.mm(pa[:, 0:128], la_[:], triur[:], True, True, [kla, 'triur'], [kpa])
        k.mm(pa[:, 128:256], trigtr[:], la_[:], True, True, [kla, 'trigtr'], [kpa])
        yield
        k.act(Eq[:], pa[:, 0:128], AF.Exp, [kpa], [kEq])
        k.act(Ek[:], pa[:, 0:128], AF.Exp, [kpa], [kEk], scale=-1.0)
        k.act(Es[:], pa[:, 128:256], AF.Exp, [kpa], [kEs])
        yield
        for h in range(2):
            k.stt(qi[:, h, :], q_[:], rm[:, h:h + 1], Eq[:], ALU.mult, ALU.mult, [kq, kEq, 'rm'], [kqi])
        k.tt('pool', ki[:], kT_[:], Ek[:], ALU.mult, [kkT, kEk], [kki])
        k.tt('pool', ks[:], kt_[:], Es[:], ALU.mult, [kkt, kEs], [kks])
        k.tt('pool', sg_[:], sg_[:], g_[:], ALU.mult, [ksg, kg], [ksg])
        yield
        for h in range(2):
            hp = slice(h * 64, (h + 1) * 64)
            k.mm(pb[:, h * 128:(h + 1) * 128], ki[:], qi[:, h, :], True, True, [kki, kqi], [kpb])
        yield
        for h in range(2):
            k.tt('dve', scs[h][0][:], pb[:, h * 128:(h + 1) * 128], triu[:], ALU.mult, [kpb, 'triu'], [scs[h][1]])
        yield
        for h in range(2):
            hp = slice(h * 64, (h + 1) * 64)
            k.mm(pc[:, h * 128:(h + 1) * 128], scs[h][0][:], vr_[:, h * 128:(h + 1) * 128], True, False, [scs[h][1], kvr], [kpc])
            k.mm(pc[:, h * 128:(h + 1) * 128], qi[:, h, :], S[:], False, True, [kqi, 'S'], [kpc])
        k.mm(pc[:, 256:512], ks[:], vr_[:], True, True, [kks, kvr], [kpc])
        yield
        for h in range(2):
            hp = slice(h * 64, (h + 1) * 64)
            k.stt(S[hp, :], S[hp, :].bitcast(F32), Eq[hp, 127:128], pc[hp, 256 + h * 128:256 + (h + 1) * 128], ALU.mult, ALU.add,
                  ['S', kEq, kpc], ['S'])
        k.cp('act', orw[:], pc[:, 0:256], [kpc], [korw])
        yield
        for h in range(2):
            k.act(junk[:], orw[:, h * 128:(h + 1) * 128], AF.Square, [korw], ['junk', kss], accum_out=ss_[:, h:h + 1])
        yield
        k.ts('dve', rs_[:], ss_[:], 1.0 / 128.0, EPS, ALU.mult, ALU.add, [kss], [krs])
        yield
        k.act(rs_[:], rs_[:], AF.Ln, [krs], [krs])
        k.act(rs_[:], rs_[:], AF.Exp, [krs], [krs], scale=-0.5)
        yield
        for h in range(2):
            hs = slice(h * 128, (h + 1) * 128)
            k.stt(ob_[:, hs], orw[:, hs], rs_[:, h:h + 1], gnbc[:, hs], ALU.mult, ALU.mult, [korw, krs, 'gnbc'], [kob])
        yield
        k.tt('pool', ot_[:], ob_[:], sg_[:], ALU.mult, [kob, ksg], [kot])
        k.dma('pool', oa[rows, :], ot_[:], r=[kot], final=True)

    yield from pipeline_gen(tile, NT)


def build_GLA(L, k=None):
    k = k or K()
    for _ in gen_GLA(L, k):
        pass
    return k.finish()


TWO_PI = 2.0 * math.pi
C1 = 6.28125
C2 = TWO_PI - 6.28125
PI_LO = 3.1415925


def range_sincos(k, x, xkey, shape, s_out, c_out, skey, ckey, pfx):
    if not hasattr(k, 'rr_cache'):
        k.rr_cache = {}
    if pfx not in k.rr_cache:
        k.rr_cache[pfx] = (k.sb(pfx + "kf", shape), k.sb(pfx + "ki", shape, I32), k.sb(pfx + "r", shape), k.sb(pfx + "m", shape))
    kf, ki, r, m = k.rr_cache[pfx]
    a = lambda t: t[:]
    K1, K2, K3, K4 = pfx + 'kf', pfx + 'ki', pfx + 'r', pfx + 'm'
    k.ts('dve', a(kf), x, 1.0 / TWO_PI, None, ALU.mult, None, [xkey], [K1])
    k.cp('dve', a(ki), a(kf), [K1], [K2])
    k.cp('dve', a(kf), a(ki), [K2], [K1])
    k.stt(a(r), a(kf), -C1, x, ALU.mult, ALU.add, [K1, xkey], [K3])
    k.stt(a(r), a(kf), -C2, a(r), ALU.mult, ALU.add, [K1, K3], [K3])
    k.ts('dve', a(m), a(r), math.pi, -TWO_PI, ALU.is_gt, ALU.mult, [K3], [K4])
    k.tt('dve', a(r), a(r), a(m), ALU.add, [K3, K4], [K3])
    k.ts('dve', a(m), a(r), -math.pi, TWO_PI, ALU.is_lt, ALU.mult, [K3], [K4])
    k.tt('dve', a(r), a(r), a(m), ALU.add, [K3, K4], [K3])
    k.ts('dve', a(kf), a(r), PI_LO, -PI_LO, ALU.min, ALU.max, [K3], [K1])
    k.act(s_out, a(kf), AF.Sin, [K1], [skey])
    k.ts('dve', a(r), a(r), math.pi / 2, None, ALU.add, None, [K3], [K3])
    k.ts('dve', a(m), a(r), math.pi, -TWO_PI, ALU.is_gt, ALU.mult, [K3], [K4])
    k.tt('dve', a(r), a(r), a(m), ALU.add, [K3, K4], [K3])
    k.ts('dve', a(kf), a(r), PI_LO, -PI_LO, ALU.min, ALU.max, [K3], [K1])
    k.act(c_out, a(kf), AF.Sin, [K1], [ckey])


def gen_S5(L, k):
    NT = L // 128
    NS = 1024
    uT = k.din("uT", [256, L])
    u = k.din("u", [L, 256])
    lam_re = k.din("lam_re", [NS])
    lam_im = k.din("lam_im", [NS])
    lstep = k.din("lstep", [NS])
    Bre = k.din("Bre", [2, 128, 512])
    Bim = k.din("Bim", [2, 128, 512])
    Cre = k.din("Cre", [8, 128, 32])
    Cim = k.din("Cim", [8, 128, 32])
    dsk = k.din("dsk", [256])
    triu_d = k.din("triu", [128, 128])
    iop_d = k.din("iota_p", [128, 1])
    iof_d = k.din("iota_f", [128, 128])
    y = k.dout("y", [L, 256])

    k.push_scope([("triu_s", [128, 128], F32), ("dbc", [128, 256], F32), ("BBr", [128, 2, 512], mybir.dt.float32r), ("BBi", [128, 2, 512], mybir.dt.float32r),
                  ("Pr", [128, NS], F32), ("Pi", [128, NS], F32), ("Qr", [128, 8, 128], F32), ("Qi", [128, 8, 128], F32),
                  ("L128r", [128, 8], F32), ("L128i", [128, 8], F32), ("Cr", [128, 8, 32], F32), ("nCi", [128, 8, 32], F32),
                  ("car_r", [128, 8], F32), ("car_i", [128, 8], F32), ("ntriu", [128, 128], mybir.dt.float32r), ("nCr", [128, 8, 32], mybir.dt.float32r), ("triur", [128, 128], mybir.dt.float32r), ("Crr", [128, 8, 32], mybir.dt.float32r), ("nCir", [128, 8, 32], mybir.dt.float32r)])
    triu = k.sb("triu_s", [128, 128])
    k.dma('sp', triu[:], triu_d, w=['triu'])
    iop = k.sb("iop", [128, 1])
    k.dma('sp', iop[:], iop_d, w=['iop'])
    negp = k.sb("negp", [128, 1])
    k.ts('dve', negp[:], iop[:], -1.0, None, ALU.mult, None, ['iop'], ['negp'])
    iof = k.sb("iof", [128, 128])
    k.dma('sp', iof[:], iof_d, w=['iof'])
    dbc = k.bcast_row("dbc", dsk, 256)
    R = [128, NS]
    lr = k.bcast_row("lr", lam_re, NS)
    li = k.bcast_row("li", lam_im, NS)
    dl = k.bcast_row("dl", lstep, NS)
    k.ts('dve', lr[:], lr[:], -1e-4, None, ALU.min, None, ['lr'], ['lr'])
    k.act(dl[:], dl[:], AF.Exp, ['dl'], ['dl'])
    a_ = k.sb("a_", R)
    th = k.sb("th", R)
    k.tt('dve', a_[:], lr[:], dl[:], ALU.mult, ['lr', 'dl'], ['a_'])
    k.tt('dve', th[:], li[:], dl[:], ALU.mult, ['li', 'dl'], ['th'])
    sn = k.sb("sn", R)
    cs = k.sb("cs", R)
    range_sincos(k, th[:], 'th', R, sn[:], cs[:], 'sn', 'cs', 'rr_')
    ea = k.sb("ea", R)
    k.act(ea[:], a_[:], AF.Exp, ['a_'], ['ea'])
    nr = k.sb("nr", R)
    ni = k.sb("ni", R)
    k.tt('dve', nr[:], ea[:], cs[:], ALU.mult, ['ea', 'cs'], ['nr'])
    k.ts('dve', nr[:], nr[:], -1.0, None, ALU.add, None, ['nr'], ['nr'])
    k.tt('dve', ni[:], ea[:], sn[:], ALU.mult, ['ea', 'sn'], ['ni'])
    den = k.sb("den", R)
    t0 = k.sb("t0", R)
    k.tt('dve', den[:], lr[:], lr[:], ALU.mult, ['lr'], ['den'])
    k.tt('dve', t0[:], li[:], li[:], ALU.mult, ['li'], ['t0'])
    k.tt('dve', den[:], den[:], t0[:], ALU.add, ['den', 't0'], ['den'])
    k.recip(den[:], den[:], ['den'], ['den'])
    gr = k.sb("gr", R)
    gi = k.sb("gi", R)
    k.tt('dve', gr[:], nr[:], lr[:], ALU.mult, ['nr', 'lr'], ['gr'])
    k.tt('dve', t0[:], ni[:], li[:], ALU.mult, ['ni', 'li'], ['t0'])
    k.tt('dve', gr[:], gr[:], t0[:], ALU.add, ['gr', 't0'], ['gr'])
    k.tt('dve', gr[:], gr[:], den[:], ALU.mult, ['gr', 'den'], ['gr'])
    k.tt('dve', gi[:], ni[:], lr[:], ALU.mult, ['ni', 'lr'], ['gi'])
    k.tt('dve', t0[:], nr[:], li[:], ALU.mult, ['nr', 'li'], ['t0'])
    k.tt('dve', gi[:], gi[:], t0[:], ALU.subtract, ['gi', 't0'], ['gi'])
    k.tt('dve', gi[:], gi[:], den[:], ALU.mult, ['gi', 'den'], ['gi'])
    Br = k.sb("Br", [128, 2, 512])
    Bi = k.sb("Bi", [128, 2, 512])
    BBr = k.sb("BBr", [128, 2, 512])
    BBi = k.sb("BBi", [128, 2, 512])
    for hc in range(2):
        k.dma('sp', Br[:, hc, :], Bre[hc], w=[f'Br{hc}'])
        k.dma('sp', Bi[:, hc, :], Bim[hc], w=[f'Bi{hc}'])
    grv = gr[:].rearrange("p (h n) -> p h n", h=2)
    giv = gi[:].rearrange("p (h n) -> p h n", h=2)
    t0v = t0[:].rearrange("p (h n) -> p h n", h=2)
    BK = ['Br0', 'Br1', 'Bi0', 'Bi1']
    k.tt('dve', BBr[:], grv, Br[:], ALU.mult, ['gr'] + BK, ['BBr'])
    k.tt('dve', t0v, giv, Bi[:], ALU.mult, ['gi'] + BK, ['t0'])
    k.tt('dve', BBr[:], BBr[:].bitcast(F32), t0v, ALU.subtract, ['BBr', 't0'], ['BBr'])
    k.tt('dve', BBi[:], grv, Bi[:], ALU.mult, ['gr'] + BK, ['BBi'])
    k.tt('dve', t0v, giv, Br[:], ALU.mult, ['gi'] + BK, ['t0'])
    k.tt('dve', BBi[:], BBi[:].bitcast(F32), t0v, ALU.add, ['BBi', 't0'], ['BBi'])
    ang = k.sb("ang", R)
    k.ts('dve', ang[:], th[:], iop[:, 0:1], None, ALU.mult, None, ['th', 'iop'], ['ang'])
    Pr = k.sb("Pr", R)
    Pi = k.sb("Pi", R)
    range_sincos(k, ang[:], 'ang', R, sn[:], cs[:], 'sn', 'cs', 'rr_')
    k.act(ea[:], a_[:], AF.Exp, ['a_', 'negp'], ['ea'], scale=negp[:, 0:1])
    k.tt('dve', Pr[:], ea[:], cs[:], ALU.mult, ['ea', 'cs'], ['Pr'])
    k.stt(Pi[:], ea[:], -1.0, sn[:], ALU.mult, ALU.mult, ['ea', 'sn'], ['Pi'])
    Cs = [128, 8]
    lrc = k.sb("lrc", Cs)
    lic = k.sb("lic", Cs)
    dlc = k.sb("dlc", Cs)
    cv = lambda d: d.rearrange("(blk p) -> p blk", p=128)
    k.dma('sp', lrc[:], cv(lam_re), w=['lrc'], allow_slow_non_contiguous=True)
    k.dma('sp', lic[:], cv(lam_im), w=['lic'], allow_slow_non_contiguous=True)
    k.dma('sp', dlc[:], cv(lstep), w=['dlc'], allow_slow_non_contiguous=True)
    k.ts('dve', lrc[:], lrc[:], -1e-4, None, ALU.min, None, ['lrc'], ['lrc'])
    k.act(dlc[:], dlc[:], AF.Exp, ['dlc'], ['dlc'])
    ac = k.sb("ac", Cs)
    thc = k.sb("thc", Cs)
    k.tt('dve', ac[:], lrc[:], dlc[:], ALU.mult, ['lrc', 'dlc'], ['ac'])
    k.tt('dve', thc[:], lic[:], dlc[:], ALU.mult, ['lic', 'dlc'], ['thc'])
    Qr = k.sb("Qr", [128, 8, 128])
    Qi = k.sb("Qi", [128, 8, 128])
    angv = ang[:].rearrange("p (b t) -> p b t", b=8)
    eav = ea[:].rearrange("p (b t) -> p b t", b=8)
    for blk in range(8):
        k.ts('dve', angv[:, blk, :], iof[:], thc[:, blk:blk + 1], None, ALU.mult, None, ['iof', 'thc'], ['ang'])
    range_sincos(k, ang[:], 'ang', R, sn[:], cs[:], 'sn', 'cs', 'rr_')
    for blk in range(8):
        k.act(eav[:, blk, :], iof[:], AF.Exp, ['iof', 'ac'], ['ea'], scale=ac[:, blk:blk + 1])
    k.tt('dve', Qr[:].rearrange("p b t -> p (b t)"), ea[:], cs[:], ALU.mult, ['ea', 'cs'], ['Qr'])
    k.tt('dve', Qi[:].rearrange("p b t -> p (b t)"), ea[:], sn[:], ALU.mult, ['ea', 'sn'], ['Qi'])
    a128 = k.sb("a128", Cs)
    s128 = k.sb("s128", Cs)
    c128 = k.sb("c128", Cs)
    L128r = k.sb("L128r", Cs)
    L128i = k.sb("L128i", Cs)
    k.ts('dve', a128[:], thc[:], 128.0, None, ALU.mult, None, ['thc'], ['a128'])
    range_sincos(k, a128[:], 'a128', Cs, s128[:], c128[:], 's128', 'c128', 'rc_')
    k.act(a128[:], ac[:], AF.Exp, ['ac', 's128', 'c128'], ['a128'], scale=128.0)
    k.tt('dve', L128r[:], a128[:], c128[:], ALU.mult, ['a128', 'c128'], ['L128r'])
    k.tt('dve', L128i[:], a128[:], s128[:], ALU.mult, ['a128', 's128'], ['L128i'])
    Cr = k.sb("Cr", [128, 8, 32])
    nCi = k.sb("nCi", [128, 8, 32])
    k.dma('sp', Cr[:], Cre.rearrange("b p c -> p b c"), w=['Cr'])
    k.dma('sp', nCi[:], Cim.rearrange("b p c -> p b c"), w=['nCi'])
    k.ts('dve', nCi[:], nCi[:], -1.0, None, ALU.mult, None, ['nCi'], ['nCi'])
    car_r = k.sb("car_r", Cs)
    car_i = k.sb("car_i", Cs)
    k.memset('dve', car_r[:], 0.0, ['car_r0', 'car_r1'])
    k.memset('dve', car_i[:], 0.0, ['car_i0', 'car_i1'])
    ntriu = k.sb("ntriu", [128, 128])
    k.ts('dve', ntriu[:], triu[:], -1.0, None, ALU.mult, None, ['triu'], ['ntriu'])
    nCr = k.sb("nCr", [128, 8, 32])
    k.ts('dve', nCr[:], Cr[:], -1.0, None, ALU.mult, None, ['Cr'], ['nCr'])
    triur = k.sb("triur", [128, 128])
    k.cp('dve', triur[:], triu[:], ['triu'], ['triur'])
    Crr = k.sb("Crr", [128, 8, 32])
    k.cp('dve', Crr[:], Cr[:], ['Cr'], ['Crr'])
    nCir = k.sb("nCir", [128, 8, 32])
    k.cp('dve', nCir[:], nCi[:], ['nCi'], ['nCir'])
    k.pop_scope()
    if hasattr(k, 'rr_cache'):
        del k.rr_cache
    def ring(nm, shape, n, dt=F32):
        return [k.sb(f"{nm}{j}", shape, dt) for j in range(n)]
    FR_ = mybir.dt.float32r
    uTt = ring("uTt", [128, 128], 3)
    uTr = ring("uTr", [128, 128], 3, FR_)
    ut = ring("ut", [128, 128], 5)
    yo = ring("yo", [128, 128], 9)
    m1, m2, m3, m4 = ring("m1_", [128, 512], 3, FR_), ring("m2_", [128, 512], 3, FR_), ring("m3_", [128, 512], 3, FR_), ring("m4_", [128, 512], 3, FR_)
    Xtr, Xti = ring("Xtr", [128, 512], 3), ring("Xti", [128, 512], 3)
    Gr, Gi = ring("Gr", [128, 4, 128], 4), ring("Gi", [128, 4, 128], 4)
    n1, n2, n3, n4 = ring("n1_", [128, 512], 3, FR_), ring("n2_", [128, 512], 3, FR_), ring("n3_", [128, 512], 3, FR_), ring("n4_", [128, 512], 3, FR_)
    Hr, Hi = ring("Hr", [128, 4, 128], 3), ring("Hi", [128, 4, 128], 3)
    cc1 = [k.sb(f"cc1_{h}", [128, 4]) for h in range(2)]
    cc2 = [k.sb(f"cc2_{h}", [128, 4]) for h in range(2)]
    psXr = k.ps("psXr", [128, 512])
    psXi = k.ps("psXi", [128, 512])
    psGr = k.ps("psGr", [128, 512])
    psGi = k.ps("psGi", [128, 512])
    psY = k.ps("psY", [128, 512])
    fl = lambda t: t[:].rearrange("p b t -> p (b t)")

    def item(j):
        i, hc = divmod(j, 2)
        rows = slice(i * 128, (i + 1) * 128)
        cs_ = slice(hc * 512, (hc + 1) * 512)
        bs = slice(hc * 4, (hc + 1) * 4)
        def T(lst, nm):
            q = j % len(lst)
            return lst[q], f'{nm}{q}'
        uT_, kuT = T(uTt, 'uTt'); uR_, kuR = T(uTr, 'uTr'); ut_, kut = T(ut, 'ut'); yo_, kyo = T(yo, 'yo')
        m1_, km1 = T(m1, 'm1'); m2_, km2 = T(m2, 'm2'); m3_, km3 = T(m3, 'm3'); m4_, km4 = T(m4, 'm4')
        Xr_, kXr = T(Xtr, 'Xtr'); Xi_, kXi = T(Xti, 'Xti'); Gr_, kGr = T(Gr, 'Gr'); Gi_, kGi = T(Gi, 'Gi')
        n1_, kn1 = T(n1, 'n1'); n2_, kn2 = T(n2, 'n2'); n3_, kn3 = T(n3, 'n3'); n4_, kn4 = T(n4, 'n4')
        Hr_, kHr = T(Hr, 'Hr'); Hi_, kHi = T(Hi, 'Hi')
        k.dma('sp', uT_[:], uT[hc * 128:(hc + 1) * 128, rows], w=[kuT])
        k.dma('sp', ut_[:], u[rows, hc * 128:(hc + 1) * 128], w=[kut])
        yield
        k.cp('act', uR_[:], uT_[:], [kuT], [kuR])
        yield
        k.mm(psXr[:], uR_[:], BBr[:, hc, :], True, True, [kuR, 'BBr'], ['psXr'])
        k.mm(psXi[:], uR_[:], BBi[:, hc, :], True, True, [kuR, 'BBi'], ['psXi'])
        yield
        k.tt('dve', m1_[:], psXr[:], Pr[:, cs_], ALU.mult, ['psXr', 'Pr'], [km1])
        k.tt('dve', m3_[:], psXr[:], Pi[:, cs_], ALU.mult, ['psXr', 'Pi'], [km3])
        k.tt('dve', m2_[:], psXi[:], Pi[:, cs_], ALU.mult, ['psXi', 'Pi'], [km2])
        k.tt('dve', m4_[:], psXi[:], Pr[:, cs_], ALU.mult, ['psXi', 'Pr'], [km4])
        yield
        k.tt('pool', yo_[:], ut_[:], dbc[:, hc * 128:(hc + 1) * 128], ALU.mult, [kut, 'dbc'], [kyo])
        yield
        for nb in range(4):
            ns = slice(nb * 128, (nb + 1) * 128)
            k.mm(psGr[:, ns], m1_[:, ns], triur[:], True, False, [km1, 'triur'], ['psGr'])
            k.mm(psGr[:, ns], m2_[:, ns], ntriu[:], False, True, [km2, 'ntriu'], ['psGr'])
            k.mm(psGi[:, ns], m3_[:, ns], triur[:], True, False, [km3, 'triur'], ['psGi'])
            k.mm(psGi[:, ns], m4_[:, ns], triur[:], False, True, [km4, 'triur'], ['psGi'])
        yield
        for nb in range(4):
            ns = slice(nb * 128, (nb + 1) * 128)
            k.act(Gr_[:, nb, :], psGr[:, ns], AF.Identity, ['psGr', f'car_r{hc}'], [kGr], bias=car_r[:, hc * 4 + nb:hc * 4 + nb + 1])
            k.act(Gi_[:, nb, :], psGi[:, ns], AF.Identity, ['psGi', f'car_i{hc}'], [kGi], bias=car_i[:, hc * 4 + nb:hc * 4 + nb + 1])
        yield
        gr127 = Gr_[:, :, 127]
        gi127 = Gi_[:, :, 127]
        CK = [f'cc1{hc}', f'cc2{hc}']
        k.tt('dve', cc1[hc][:], L128r[:, bs], gr127, ALU.mult, ['L128r', kGr], [CK[0]])
        k.tt('dve', cc2[hc][:], L128i[:, bs], gi127, ALU.mult, ['L128i', kGi], [CK[1]])
        k.tt('dve', car_r[:, bs], cc1[hc][:], cc2[hc][:], ALU.subtract, CK, [f'car_r{hc}'])
        k.tt('dve', cc1[hc][:], L128r[:, bs], gi127, ALU.mult, ['L128r', kGi], [CK[0]])
        k.tt('dve', cc2[hc][:], L128i[:, bs], gr127, ALU.mult, ['L128i', kGr], [CK[1]])
        k.tt('dve', car_i[:, bs], cc1[hc][:], cc2[hc][:], ALU.add, CK, [f'car_i{hc}'])
        yield
        qr = Qr[:, bs, :].rearrange("p b t -> p (b t)")
        qi = Qi[:, bs, :].rearrange("p b t -> p (b t)")
        k.tt('dve', n1_[:], fl(Gr_), qr, ALU.mult, [kGr, 'Qr'], [kn1])
        k.tt('dve', n2_[:], fl(Gi_), qi, ALU.mult, [kGi, 'Qi'], [kn2])
        k.tt('dve', n3_[:], fl(Gi_), qr, ALU.mult, [kGi, 'Qr'], [kn3])
        k.tt('dve', n4_[:], fl(Gr_), qi, ALU.mult, [kGr, 'Qi'], [kn4])
        yield
        for nb in range(4):
            blk = hc * 4 + nb
            ns = slice(nb * 128, (nb + 1) * 128)
            yo_s = psY[:, blk * 32:(blk + 1) * 32]
            k.mm(yo_s, n1_[:, ns], Crr[:, blk, :], True, False, [kn1, 'Crr'], ['psY'])
            k.mm(yo_s, n2_[:, ns], nCr[:, blk, :], False, False, [kn2, 'nCr'], ['psY'])
            k.mm(yo_s, n3_[:, ns], nCir[:, blk, :], False, False, [kn3, 'nCir'], ['psY'])
            k.mm(yo_s, n4_[:, ns], nCir[:, blk, :], False, True, [kn4, 'nCir'], ['psY'])
        yield
        k.tt('dve', yo_[:], yo_[:], psY[:, hc * 128:(hc + 1) * 128], ALU.add, [kyo, 'psY'], [kyo])
        yield
        k.dma('pool', y[rows, hc * 128:(hc + 1) * 128], yo_[:], r=[kyo], final=True)

    yield from pipeline_gen(item, 2 * NT)


def build_S5(L, k=None):
    k = k or K()
    for _ in gen_S5(L, k):
        pass
    return k.finish()


def s5_host_inputs(s, proj_u, prm):
    gs = slice(16 * s, 16 * s + 16)
    cs = slice(256 * s, 256 * s + 256)
    uc = np.ascontiguousarray(proj_u[:, cs])
    Bre = np.zeros((2, 128, 512), np.float32)
    Bim = np.zeros((2, 128, 512), np.float32)
    Cre = np.zeros((8, 128, 32), np.float32)
    Cim = np.zeros((8, 128, 32), np.float32)
    b_re, b_im = prm['s5_b_re'][gs], prm['s5_b_im'][gs]
    c_re, c_im = prm['s5_c_re'][gs], prm['s5_c_im'][gs]
    for g in range(16):
        hc, gl = g // 8, g % 8
        Bre[hc, gl * 16:(gl + 1) * 16, gl * 64:(gl + 1) * 64] = b_re[g].T
        Bim[hc, gl * 16:(gl + 1) * 16, gl * 64:(gl + 1) * 64] = b_im[g].T
        blk, g2 = g // 2, g % 2
        Cre[blk, g2 * 64:(g2 + 1) * 64, g2 * 16:(g2 + 1) * 16] = c_re[g].T
        Cim[blk, g2 * 64:(g2 + 1) * 64, g2 * 16:(g2 + 1) * 16] = c_im[g].T
    return dict(uT=np.ascontiguousarray(uc.T), u=uc,
                lam_re=np.ascontiguousarray(prm['s5_lambda_re'][gs].reshape(-1)),
                lam_im=np.ascontiguousarray(prm['s5_lambda_im'][gs].reshape(-1)),
                lstep=np.ascontiguousarray(np.repeat(prm['s5_log_step'][gs], 64)),
                Bre=Bre, Bim=Bim, Cre=Cre, Cim=Cim, dsk=np.ascontiguousarray(prm['s5_d'][cs]),
                triu=np.triu(np.ones((128, 128), np.float32)),
                iota_p=np.arange(128, dtype=np.float32).reshape(128, 1),
                iota_f=np.tile(np.arange(128, dtype=np.float32)[None], (128, 1)))


GELU_C = 1.5957691216057308


def gen_LRU(L, k):
    TT = 512
    NCH = L // TT
    xbT = k.din("xbT", [256, L])
    gateT = k.din("gateT", [256, L])
    cw_d = k.din("cw", [128, 2, 4])
    cb_d = k.din("cb", [128, 2])
    Wa_d = k.din("Wa", [2, 128, 128])
    Wx_d = k.din("Wx", [2, 128, 128])
    ba_d = k.din("ba", [128, 2])
    bx_d = k.din("bx", [128, 2])
    lam_d = k.din("lam", [128, 2])
    odT = k.dout("odT", [256, L])
    cw = k.sb("cw_s", [128, 2, 4])
    cb = k.sb("cb_s", [128, 2])
    Wa = k.sb("Wa_s", [128, 2, 128])
    Wx = k.sb("Wx_s", [128, 2, 128])
    ba = k.sb("ba_s", [128, 2])
    bx = k.sb("bx_s", [128, 2])
    c8 = k.sb("c8", [128, 2])
    k.dma('sp', cw[:], cw_d, w=['cw'])
    k.dma('sp', cb[:], cb_d, w=['cb'])
    k.dma('sp', Wa[:], Wa_d.rearrange("b p n -> p b n"), w=['Wa'])
    k.dma('sp', Wx[:], Wx_d.rearrange("b p n -> p b n"), w=['Wx'])
    k.dma('sp', ba[:], ba_d, w=['ba'])
    k.dma('sp', bx[:], bx_d, w=['bx'])
    k.dma('sp', c8[:], lam_d, w=['c8'])
    k.act(c8[:], c8[:], AF.Exp, ['c8'], ['c8'], scale=-1.0)
    k.act(c8[:], c8[:], AF.Ln, ['c8'], ['c8'], bias=1.0)
    k.ts('dve', c8[:], c8[:], -8.0, None, ALU.mult, None, ['c8'], ['c8'])
    hlast = k.sb("hlast", [128, 2])
    k.memset('dve', hlast[:], 0.0, ['hlast0', 'hlast1'])

    def ring(nm, shape, n):
        return [k.sb(f"{nm}{j}", shape) for j in range(n)]
    xh = ring("xh", [128, TT + 3], 3)
    gt = ring("gt", [128, TT], 8)
    xc = ring("xc", [128, TT], 5)
    r, ig, a, a2 = ring("r", [128, TT], 2), ring("ig", [128, TT], 3), ring("a", [128, TT], 5), ring("a2", [128, TT], 3)
    bt = ring("bt", [128, TT], 4)
    g2 = ring("g2", [128, TT], 5)
    h = ring("h", [128, TT], 2)
    ot = ring("ot", [128, TT], 3)
    psR = k.ps("psR", [128, TT])
    psI = k.ps("psI", [128, TT])

    def item(n):
        c, pb = divmod(n, 2)
        prow = slice(pb * 128, (pb + 1) * 128)
        def T(lst, nm):
            j = n % len(lst)
            return lst[j], f'{nm}{j}'
        xh_, kxh = T(xh, 'xh'); gt_, kgt = T(gt, 'gt'); xc_, kxc = T(xc, 'xc'); r_, kr = T(r, 'r'); ig_, kig = T(ig, 'ig')
        a_, ka = T(a, 'a'); a2_, ka2 = T(a2, 'a2'); bt_, kbt = T(bt, 'bt'); g2_, kg2 = T(g2, 'g2'); h_, kh = T(h, 'h'); ot_, kot = T(ot, 'ot')
        if c == 0:
            k.memset('dve', xh_[:, 0:3], 0.0, [kxh + 'h'])
            k.dma('sp', xh_[:, 3:TT + 3], xbT[prow, 0:TT], w=[kxh])
        else:
            k.dma('sp', xh_[:, 0:TT + 3], xbT[prow, c * TT - 3:(c + 1) * TT], w=[kxh, kxh + 'h'])
        k.dma('sp', gt_[:], gateT[prow, c * TT:(c + 1) * TT], w=[kgt])
        yield
        xk = [kxh, kxh + 'h']
        k.ts('dve', xc_[:], xh_[:, 3:TT + 3], cw[:, pb, 3:4], cb[:, pb:pb + 1], ALU.mult, ALU.add, xk + ['cw', 'cb'], [kxc])
        for j in (2, 1, 0):
            k.stt(xc_[:], xh_[:, j:j + TT], cw[:, pb, j:j + 1], xc_[:], ALU.mult, ALU.add, xk + ['cw', kxc], [kxc])
        yield
        k.mm(psR[:], Wa[:, pb, :], xc_[:], True, True, ['Wa', kxc], ['psR'])
        k.mm(psI[:], Wx[:, pb, :], xc_[:], True, True, ['Wx', kxc], ['psI'])
        yield
        k.act(r_[:], psR[:], AF.Sigmoid, ['psR', 'ba'], [kr], bias=ba[:, pb:pb + 1])
        k.act(ig_[:], psI[:], AF.Sigmoid, ['psI', 'bx'], [kig], bias=bx[:, pb:pb + 1])
        k.act(a_[:], r_[:], AF.Exp, [kr, 'c8'], [ka], scale=c8[:, pb:pb + 1])
        k.act(a2_[:], a_[:], AF.Square, [ka], [ka2])
        k.act(a2_[:], a2_[:], AF.Sqrt, [ka2], [ka2], scale=-1.0, bias=1.0)
        k.act(g2_[:], gt_[:], AF.Square, [kgt], [kg2])
        k.act(g2_[:], g2_[:], AF.Copy, [kg2], [kg2], scale=0.044715, bias=1.0)
        yield
        k.tt('dve', bt_[:], ig_[:], xc_[:], ALU.mult, [kig, kxc], [kbt])
        k.tt('dve', bt_[:], bt_[:], a2_[:], ALU.mult, [kbt, ka2], [kbt])
        k.tt('dve', g2_[:], g2_[:], gt_[:], ALU.mult, [kg2, kgt], [kg2])
        yield
        k.act(g2_[:], g2_[:], AF.Sigmoid, [kg2], [kg2], scale=GELU_C)
        yield
        k.P.op('dve', lambda e: e.tensor_tensor_scan(out=h_[:], data0=a_[:], data1=bt_[:], initial=hlast[:, pb:pb + 1],
                                                     op0=ALU.mult, op1=ALU.add),
               reads=[ka, kbt, f'hlast{pb}'], writes=[kh])
        k.cp('dve', hlast[:, pb:pb + 1], h_[:, TT - 1:TT], [kh], [f'hlast{pb}'])
        k.tt('dve', g2_[:], g2_[:], gt_[:], ALU.mult, [kg2, kgt], [kg2])
        k.tt('dve', ot_[:], h_[:], g2_[:], ALU.mult, [kh, kg2], [kot])
        yield
        k.dma('pool', odT[prow, c * TT:(c + 1) * TT], ot_[:], r=[kot], final=True)

    yield from pipeline_gen(item, 2 * NCH)


def build_LRU(L, k=None):
    k = k or K()
    for _ in gen_LRU(L, k):
        pass
    return k.finish()


def lru_host_inputs(s, xb, gate, prm):
    cs = slice(256 * s, 256 * s + 256)
    col = lambda v: np.ascontiguousarray(v[cs].reshape(2, 128).T)
    Wa = np.zeros((2, 128, 128), np.float32)
    Wx = np.zeros((2, 128, 128), np.float32)
    for pb in range(2):
        for bl in range(2):
            blk = 4 * s + 2 * pb + bl
            Wa[pb, bl * 64:(bl + 1) * 64, bl * 64:(bl + 1) * 64] = prm['lru_w_a'][blk]
            Wx[pb, bl * 64:(bl + 1) * 64, bl * 64:(bl + 1) * 64] = prm['lru_w_x'][blk]
    cw = np.ascontiguousarray(prm['lru_conv_w'][:, cs].reshape(4, 2, 128).transpose(2, 1, 0))
    return dict(xbT=np.ascontiguousarray(xb[:, cs].T), gateT=np.ascontiguousarray(gate[:, cs].T), cw=cw,
                cb=col(prm['lru_conv_b']), Wa=Wa, Wx=Wx, ba=col(prm['lru_b_a']), bx=col(prm['lru_b_x']),
                lam=col(prm['lru_lambda']))


GN_EPS = 64e-5
NLEV = 5


def build_RWKV(L, k=None, NH=4, fr=False, CH=64):
    k = k or K()
    NT = L // 128
    W = NH * 64
    NG = NH // 4
    FR = mybir.dt.float32r if fr else F32
    rd = (lambda ap: ap.bitcast(F32)) if fr else (lambda ap: ap)
    NCK = 128 // CH
    nlev = 5 if CH == 64 else 6
    frc = fr and CH == 128
    FRC = mybir.dt.float32r if frc else F32
    rdc = (lambda ap: ap.bitcast(F32)) if frc else (lambda ap: ap)
    lhc = (lambda ap: ap) if frc else rd
    prkv = [k.din(nm, [L, W]) for nm in ("pr", "pk", "pv")]
    mu1 = k.din("mu1", [3 * W])
    pls = [k.din("plw", [64, L]), k.din("pla", [64, L]), k.din("plg", [128, L])]
    mul = k.din("mul", [128, 3])
    w2 = k.din("w2", [64, W])
    a2 = k.din("a2", [64, W])
    g2 = k.din("g2", [128, W])
    vecs = k.din("vecs", [7, W])
    ident_d = k.din("ident", [128, 128])
    triw_d = k.din("triw", [3, 128, 128])
    mask5_d = k.din("mask5", [128, 640])
    rowm_d = k.din("rowm", [128, 2])
    oc = k.dout("oc", [L, W])

    k.consts(ident_d)
    triw = k.sb("triw_s", [128, 3, 128])
    k.dma('sp', triw[:], triw_d.rearrange("a p n -> p a n"), w=['triw'])
    mask5 = k.sb("mask5_s", [128, 640])
    k.dma('sp', mask5[:], mask5_d, w=['mask5'])
    rowm = k.sb("rowm_s", [128, 2])
    k.dma('sp', rowm[:], rowm_d, w=['rowm'])
    mu1bc = k.bcast_row("mu1bc", mu1, 3 * W)
    vb = [k.bcast_row(f"vb{i}", vecs[i], W) for i in range(7)]
    w0bc, a0bc, kkbc, kabc, rkbc, lngbc, lnbbc = vb
    VK = [f"vb{i}" for i in range(7)]
    muls = k.sb("muls", [128, 3])
    k.dma('sp', muls[:], mul, w=['muls'])
    w2s = k.sb("w2s", [64, W])
    a2s = k.sb("a2s", [64, W])
    k.dma('sp', w2s[:], w2, w=['w2s'])
    k.dma('sp', a2s[:], a2, w=['a2s'])
    g2s = k.sb("g2s", [128, W])
    k.dma('sp', g2s[:], g2, w=['g2s'])
    ST = [k.sb(f"ST{i}", [64, 64], FRC) for i in range(NH)]
    zt = k.sb("zt", [128, W])
    k.memset('dve', zt[:], 0.0, ['zt'])
    for i in range(NH):
        k.cp('dve', ST[i][:], zt[0:64, 0:64], ['zt'], [f'ST{i}'])
    P1s = k.sb("P1s", [128, W], FRC)
    Us = k.sb("Us", [128, W], FRC)
    k.cp('dve', P1s[:], zt[:], ['zt'], ['P1s'])
    k.cp('dve', Us[:], zt[:], ['zt'], ['Us'])

    pt = [k.sb(f"pt{i}", [128, 3 * W]) for i in range(2)]
    pp = [k.sb(f"pp{i}", [128, 3 * W]) for i in range(2)]
    lt = [k.sb(f"lt{i}", [128, 3, 128]) for i in range(2)]
    lp = [k.sb(f"lp{i}", [128, 3, 128]) for i in range(2)]
    for i_ in range(2):
        k.memset('pool', lt[i_][:], 0.0, [f'lt{i_}0', f'lt{i_}1', f'lt{i_}2'])
        k.memset('pool', lp[i_][:], 0.0, [f'lp{i_}0', f'lp{i_}1', f'lp{i_}2', f'lp{i_}z'])
    pm = k.sb("pm", [128, 3 * W])
    vr = k.sb("vr", [128, W], FR)
    lm = k.sb("lm", [128, 3, 128])
    sw = k.sb("sw", [128, W])
    av = k.sb("av", [128, W])
    gv = k.sb("gv", [128, W])
    kkr = k.sb("kkr", [128, W])
    sq = k.sb("sq", [128, W])
    s4 = k.sb("s4", [128, NH])
    rn = k.sb("rn", [128, NH])
    nkk = k.sb("nkk", [128, W])
    kmod = k.sb("kmod", [128, W])
    kka = k.sb("kka", [128, W])
    tmp = k.sb("tmp", [128, W])
    bon = k.sb("bon", [128, NH])
    E1 = k.sb("E1", [128, W])
    E2 = k.sb("E2", [128, W])
    E3 = k.sb("E3", [128, W])
    E4 = k.sb("E4", [128, W])
    E1T = k.sb("E1T", [64, NH, 128])
    At = k.sb("At", [128, W])
    Bs = k.sb("Bs", [128, W])
    Ks = k.sb("Ks", [128, W])
    Rt = k.sb("Rt", [128, W])
    Bfm = [k.sb(f"Bfm{c}", [128, W]) for c in range(2)]
    Kfm = [k.sb(f"Kfm{c}", [128, W]) for c in range(2)]
    FT = [k.sb(f"FT{h}", [64, 4, 128], FR) for h in range(NH)]
    A5 = [k.sb(f"A5_{h}", [128, 640], FR) for h in range(NH)]
    NL = [k.sb(f"NL_{h}", [128, 256], FR) for h in range(NH)]
    PQ = [k.sb(f"PQ_{h}", [128, 256], FR) for h in range(NH)]
    W1 = k.sb("W1", [128, W], FR)
    U1 = k.sb("U1", [128, W])
    ysb = k.sb("ysb", [128, W])
    yc = k.sb("yc", [128, W])
    m4 = k.sb("m4", [128, NH])
    r4 = k.sb("r4", [128, NH])
    ot = [k.sb(f"ot{i}", [128, W]) for i in range(2)]
    B = [k.ps(f"psB{i}", [128, 512]) for i in range(8)]
    bk = lambda i: f'psB{i}'
    v3 = lambda t: t.rearrange("p (h j) -> p h j", h=NH)
    bc4 = lambda t: t.unsqueeze(2).broadcast_to([128, NH, 64])

    for i in range(NT):
        b = i % 2
        rows = slice(i * 128, (i + 1) * 128)
        PK, PPK, LTK, LPK = [], [], [], []
        for q in range(3):
            cq = slice(q * W, (q + 1) * W)
            k.dma('sp', pt[b][:, cq], prkv[q][rows, :], w=[f'pt{b}{q}'])
            PK.append(f'pt{b}{q}')
            if i == 0:
                k.dma('sp', pp[b][1:128, cq], prkv[q][0:127, :], w=[f'pp{b}{q}'])
            else:
                k.dma('sp', pp[b][:, cq], prkv[q][i * 128 - 1:i * 128 + 127, :], w=[f'pp{b}{q}'])
            PPK.append(f'pp{b}{q}')
            nr = pls[q].shape[0]
            k.dma('sp', lt[b][0:nr, q, :], pls[q][:, rows], w=[f'lt{b}{q}'])
            LTK.append(f'lt{b}{q}')
            if i == 0:
                k.dma('sp', lp[b][0:nr, q, 1:128], pls[q][:, 0:127], w=[f'lp{b}{q}'])
            else:
                k.dma('sp', lp[b][0:nr, q, :], pls[q][:, i * 128 - 1:i * 128 + 127], w=[f'lp{b}{q}'])
            LPK.append(f'lp{b}{q}')
        if i == 0:
            k.memset('pool', pp[b][0:1, :], 0.0, [f'pp{b}z'])
            k.memset('pool', lp[b][:, :, 0:1], 0.0, [f'lp{b}z'])
            PPK.append(f'pp{b}z')
            LPK.append(f'lp{b}z')
        k.tt('pool', pm[:], pp[b][:], pt[b][:], ALU.subtract, PPK + PK, ['pm'])
        k.tt('pool', pm[:], pm[:], mu1bc[:], ALU.mult, ['pm', 'mu1bc'], ['pm'])
        k.tt('pool', pm[:], pm[:], pt[b][:], ALU.add, ['pm'] + PK, ['pm'])
        r_, k_, v_ = pm[:, 0:W], pm[:, W:2 * W], pm[:, 2 * W:3 * W]
        k.cp('act', vr[:], v_, ['pm'], ['vr'])
        LK = LTK + LPK
        k.tt('dve', lm[:], lp[b][:], lt[b][:], ALU.subtract, LK, ['lm'])
        for blk in range(3):
            k.stt(lm[:, blk, :], lm[:, blk, :], muls[:, blk:blk + 1], lt[b][:, blk, :], ALU.mult, ALU.add,
                  ['lm', 'muls'] + LK, ['lm'])
        k.act(lm[0:64, 0, :], lm[0:64, 0, :], AF.Tanh, ['lm'], ['lm'])
        k.act(lm[:, 2, :], lm[:, 2, :], AF.Sigmoid, ['lm'], ['lm'])
        k.mm(B[0][:, 0:W], lm[0:64, 0, :], w2s[:], True, True, ['lm', 'w2s'], [bk(0)])
        k.mm(B[1][:, 0:W], lm[0:64, 1, :], a2s[:], True, True, ['lm', 'a2s'], [bk(1)])
        k.mm(B[2][:, 0:W], lm[:, 2, :], g2s[:], True, True, ['lm', 'g2s'], [bk(2)])
        k.tt('dve', sw[:], B[0][:, 0:W], w0bc[:], ALU.add, [bk(0), VK[0]], ['sw'])
        k.act(sw[:], sw[:], AF.Sigmoid, ['sw'], ['sw'])
        k.tt('dve', av[:], B[1][:, 0:W], a0bc[:], ALU.add, [bk(1), VK[1]], ['av'])
        k.act(av[:], av[:], AF.Sigmoid, ['av'], ['av'])
        k.cp('act', gv[:], B[2][:, 0:W], [bk(2)], ['gv'])
        k.tt('pool', kkr[:], k_, kkbc[:], ALU.mult, ['pm', VK[2]], ['kkr'])
        k.tt('pool', sq[:], kkr[:], kkr[:], ALU.mult, ['kkr'], ['sq'])
        k.P.op('dve', lambda e: e.tensor_reduce(out=s4[:], in_=v3(sq[:]), axis=AX.X, op=ALU.add), reads=['sq'], writes=['s4'])
        k.act(s4[:], s4[:], AF.Sqrt, ['s4'], ['s4'])
        k.ts('dve', s4[:], s4[:], 1e-12, None, ALU.max, None, ['s4'], ['s4'])
        k.recip(rn[:], s4[:], ['s4'], ['rn'])
        k.ts('dve', rn[:], rn[:], -1.0, None, ALU.mult, None, ['rn'], ['rn'])
        k.tt('dve', v3(nkk[:]), v3(kkr[:]), bc4(rn[:]), ALU.mult, ['kkr', 'rn'], ['nkk'])
        k.stt(tmp[:], av[:], -1.0, kabc[:], ALU.add, ALU.mult, ['av', VK[3]], ['tmp'])
        k.stt(kmod[:], tmp[:], 1.0, k_, ALU.add, ALU.mult, ['tmp', 'pm'], ['kmod'])
        k.stt(kka[:], nkk[:], -1.0, av[:], ALU.mult, ALU.mult, ['nkk', 'av'], ['kka'])
        k.tt('pool', tmp[:], r_, kmod[:], ALU.mult, ['pm', 'kmod', 'tmp'], ['tmp'])
        k.tt('pool', tmp[:], tmp[:], rkbc[:], ALU.mult, ['tmp', VK[4]], ['tmp'])
        k.P.op('dve', lambda e: e.tensor_reduce(out=bon[:], in_=v3(tmp[:]), axis=AX.X, op=ALU.add), reads=['tmp'], writes=['bon'])
        k.mm(B[3][:, 0:W], triw[:, 0, :], sw[:], True, True, ['triw', 'sw'], [bk(3)])
        k.mm(B[4][:, 0:W], triw[:, 1, :], sw[:], True, True, ['triw', 'sw'], [bk(4)])
        k.mm(B[5][:, 0:W], triw[:, 2, :], sw[:], True, True, ['triw', 'sw'], [bk(5)])
        for h in range(NH):
            k.mm(B[6 + h // 4][0:64, (h % 4) * 128:(h % 4 + 1) * 128], sw[:, h * 64:(h + 1) * 64], triw[:, 0, :], True, True,
                 ['sw', 'triw'], [bk(6 + h // 4)])
        k.act(E1[:], B[3][:, 0:W], AF.Exp, [bk(3)], ['E1'])
        k.act(E2[:], B[3][:, 0:W], AF.Exp, [bk(3)], ['E2'], scale=-1.0)
        k.act(E3[:], B[4][:, 0:W], AF.Exp, [bk(4)], ['E3'])
        k.act(E4[:], B[5][:, 0:W], AF.Exp, [bk(5)], ['E4'])
        for g in range(NG):
            k.act(E1T[:, 4 * g:4 * g + 4, :].rearrange("p a t -> p (a t)"), B[6 + g][0:64, :], AF.Exp, [bk(6 + g)], ['E1T'])
        k.tt('dve', At[:], nkk[:], E3[:], ALU.mult, ['nkk', 'E3'], ['At'])
        k.tt('pool', Bs[:], kka[:], E2[:], ALU.mult, ['kka', 'E2'], ['Bs'])
        k.tt('dve', Ks[:], kmod[:], E2[:], ALU.mult, ['kmod', 'E2'], ['Ks'])
        k.tt('pool', Rt[:], r_, E1[:], ALU.mult, ['pm', 'E1'], ['Rt'])
        for c in range(NCK):
            k.stt(Bfm[c][:], kka[:], rowm[:, c:c + 1], E4[:], ALU.mult, ALU.mult, ['kka', 'E4', 'rowm'], [f'Bfm{c}'])
            k.stt(Kfm[c][:], kmod[:], rowm[:, c:c + 1], E4[:], ALU.mult, ALU.mult, ['kmod', 'E4', 'rowm'], [f'Kfm{c}'])
        HS = list(range(NH))
        for h in HS:
            cs_ = slice(h * 64, (h + 1) * 64)
            for q, (src, key) in enumerate([(At, 'At'), (Bs, 'Bs'), (Ks, 'Ks'), (Rt, 'Rt')]):
                k.tr(B[h][0:64, q * 128:(q + 1) * 128], src[:, cs_], k.identf[:], [key], [bk(h)])
        for h in HS:
            k.cp('act' if h % 2 else 'dve', FT[h][:].rearrange("p a t -> p (a t)"), B[h][0:64, :], [bk(h)], [f'FT{h}'])
        for h in HS:
            AtT, BsT, KsT, RtT = (FT[h][:, q, :] for q in range(4))
            o = lambda j: B[h][:, j * 128:(j + 1) * 128]
            k.mm(o(0), BsT, AtT, True, True, [f'FT{h}'], [bk(h)])
            k.mm(o(1), AtT, BsT, True, True, [f'FT{h}'], [bk(h)])
            k.mm(o(2), KsT, AtT, True, True, [f'FT{h}'], [bk(h)])
        for h in HS:
            k.tt('dve', A5[h][:, 0:384], B[h][:, 0:384], mask5[:, 0:384], ALU.mult, [bk(h), 'mask5'], [f'A5_{h}'])
        for h in HS:
            AtT, BsT, KsT, RtT = (FT[h][:, q, :] for q in range(4))
            k.mm(B[h][:, 0:128], BsT, RtT, True, True, [f'FT{h}'], [bk(h)])
            k.mm(B[h][:, 128:256], KsT, RtT, True, True, [f'FT{h}'], [bk(h)])
        for h in HS:
            k.tt('dve', A5[h][:, 384:640], B[h][:, 0:256], mask5[:, 384:640], ALU.mult, [bk(h), 'mask5'], [f'A5b_{h}'])
            k.cp('act', NL[h][:], rd(A5[h][:, 0:256]), [f'A5_{h}'], [f'NL_{h}'])
            k.tt('pool' if not fr else 'dve', PQ[h][:].rearrange("p (a n) -> p a n", a=2), rd(A5[h][:, 0:256]).rearrange("p (a n) -> p a n", a=2),
                 k.identf[:].unsqueeze(1).broadcast_to([128, 2, 128]), ALU.add, [f'A5_{h}', 'ident'], [f'PQ_{h}'])
        for lev in range(nlev):
            for h in HS:
                N_, L_ = NL[h][:, 0:128], NL[h][:, 128:256]
                k.mm(B[h][:, 0:128], L_, N_, True, True, [f'NL_{h}'], [bk(h)])
                k.mm(B[h][:, 128:256], N_, L_, True, True, [f'NL_{h}'], [bk(h)])
            for h in HS:
                k.cp('act', NL[h][:], B[h][:, 0:256], [bk(h)], [f'NL_{h}'])
            for h in HS:
                N_, L_ = NL[h][:, 0:128], NL[h][:, 128:256]
                P_, Q_ = PQ[h][:, 0:128], PQ[h][:, 128:256]
                k.mm(B[h][:, 256:384], Q_, N_, True, True, [f'NL_{h}', f'PQ_{h}'], [bk(h)])
                k.mm(B[h][:, 384:512], P_, L_, True, True, [f'NL_{h}', f'PQ_{h}'], [bk(h)])
            for h in HS:
                k.tt('dve', PQ[h][:], B[h][:, 256:512], rd(PQ[h][:]), ALU.add, [bk(h), f'PQ_{h}'], [f'PQ_{h}'])
        for h in range(NH):
            k.mm(B[0][:, h * 64:(h + 1) * 64], A5[h][:, 256:384], vr[:, h * 64:(h + 1) * 64], True, True, [f'A5_{h}', 'vr'], [bk(0)])
        k.cp('act', W1[:], B[0][:, 0:W], [bk(0)], ['W1'])
        for h in range(NH):
            k.mm(B[1][:, h * 64:(h + 1) * 64], PQ[h][:, 0:128], W1[:, h * 64:(h + 1) * 64], True, True,
                 [f'PQ_{h}', 'W1'], [bk(1)])
        k.cp('act', U1[:], B[1][:, 0:W], [bk(1)], ['U1'])
        vsrc = vr if frc else None
        for c in range(NCK):
            cr = slice(c * CH, (c + 1) * CH)
            for h in range(NH):
                k.mm(B[2][cr, h * 64:(h + 1) * 64], lhc(FT[h][:, 0, cr]), ST[h][:], True, True, [f'FT{h}', f'ST{h}'], [bk(2)])
            k.cp('act', P1s[cr, :], B[2][cr, 0:W], [bk(2)], ['P1s'])
            for h in range(NH):
                k.mm(B[3][cr, h * 64:(h + 1) * 64], lhc(PQ[h][:, cr]), P1s[:, h * 64:(h + 1) * 64], True, True,
                     [f'PQ_{h}', 'P1s'], [bk(3)])
            k.tt('dve', Us[cr, :], B[3][cr, 0:W], U1[cr, :], ALU.add, [bk(3), 'U1'], ['Us'])
            for h in range(NH):
                hc_ = slice(h * 64, (h + 1) * 64)
                vh = vr[:, hc_] if frc else pm[:, 2 * W + h * 64:2 * W + (h + 1) * 64]
                vk = 'vr' if frc else 'pm'
                k.mm(B[6][cr, hc_], lhc(FT[h][:, 3, cr]), ST[h][:], True, False, [f'FT{h}', f'ST{h}'], [bk(6)])
                k.mm(B[6][cr, hc_], lhc(A5[h][:, 384:512][:, cr]), Us[:, hc_], False, False, [f'A5b_{h}', 'Us'], [bk(6)])
                k.mm(B[6][cr, hc_], lhc(A5[h][:, 512:640][:, cr]), vh, False, True, [f'A5b_{h}', vk], [bk(6)])
            for h in range(NH):
                hc_ = slice(h * 64, (h + 1) * 64)
                vh = pm[:, 2 * W + h * 64:2 * W + (h + 1) * 64]
                k.mm(B[7][0:64, hc_], Bfm[c][:, hc_], rdc(Us[:, hc_]), True, False, [f'Bfm{c}', 'Us'], [bk(7)])
                k.mm(B[7][0:64, hc_], Kfm[c][:, hc_], vh, False, True, [f'Kfm{c}', 'pm'], [bk(7)])
            for h in range(NH):
                hc_ = slice(h * 64, (h + 1) * 64)
                k.stt(ST[h][:], rdc(ST[h][:]), E1T[:, h, (c + 1) * CH - 1:(c + 1) * CH], B[7][0:64, hc_], ALU.mult, ALU.add,
                      [f'ST{h}', 'E1T', bk(7)], [f'ST{h}'])
        k.cp('act', ysb[:], B[6][:, 0:W], [bk(6)], ['ysb'])
        k.P.op('dve', lambda e: e.tensor_reduce(out=m4[:], in_=v3(ysb[:]), axis=AX.X, op=ALU.add), reads=['ysb'], writes=['m4'])
        k.ts('dve', m4[:], m4[:], -1.0 / 64.0, None, ALU.mult, None, ['m4'], ['m4'])
        k.tt('dve', v3(yc[:]), v3(ysb[:]), bc4(m4[:]), ALU.add, ['ysb', 'm4'], ['yc'])
        k.tt('pool', sq[:], yc[:], yc[:], ALU.mult, ['yc'], ['sq'])
        k.P.op('dve', lambda e: e.tensor_reduce(out=r4[:], in_=v3(sq[:]), axis=AX.X, op=ALU.add), reads=['sq'], writes=['r4'])
        k.ts('dve', r4[:], r4[:], 1.0 / 64.0, GN_EPS, ALU.mult, ALU.add, ['r4'], ['r4'])
        k.act(r4[:], r4[:], AF.Sqrt, ['r4'], ['r4'])
        k.recip(r4[:], r4[:], ['r4'], ['r4'])
        k.tt('dve', v3(yc[:]), v3(yc[:]), bc4(r4[:]), ALU.mult, ['yc', 'r4'], ['yc'])
        k.tt('pool', yc[:], yc[:], lngbc[:], ALU.mult, ['yc', VK[5]], ['yc'])
        k.tt('pool', yc[:], yc[:], lnbbc[:], ALU.add, ['yc', VK[6]], ['yc'])
        k.tt('dve', v3(tmp[:]), v3(v_), bc4(bon[:]), ALU.mult, ['pm', 'bon', 'tmp'], ['tmp'])
        k.tt('pool', yc[:], yc[:], tmp[:], ALU.add, ['yc', 'tmp'], ['yc'])
        k.tt('dve', ot[b][:], yc[:], gv[:], ALU.mult, ['yc', 'gv'], [f'ot{b}'])
        k.dma('pool', oc[rows, :], ot[b][:], r=[f'ot{b}'], final=True)
    return k.finish()


def build_RWKVP(L, k=None, CH=64):
    NH, fr = 8, True
    k = k or K()
    NT = L // 128
    W = NH * 64
    NG = NH // 4
    FR = mybir.dt.float32r if fr else F32
    rd = (lambda ap: ap.bitcast(F32)) if fr else (lambda ap: ap)
    NCK = 128 // CH
    nlev = 5 if CH == 64 else 6
    frc = True
    FRC = mybir.dt.float32r if frc else F32
    rdc = (lambda ap: ap.bitcast(F32)) if frc else (lambda ap: ap)
    lhc = (lambda ap: ap) if frc else rd
    prkv = [k.din(nm, [L, W]) for nm in ("pr", "pk", "pv")]
    mu1 = k.din("mu1", [3 * W])
    pls = [k.din("plw", [64, L]), k.din("pla", [64, L]), k.din("plg", [128, L])]
    mul = k.din("mul", [128, 3])
    w2 = k.din("w2", [64, W])
    a2 = k.din("a2", [64, W])
    g2 = k.din("g2", [128, W])
    vecs = k.din("vecs", [7, W])
    ident_d = k.din("ident", [128, 128])
    triw_d = k.din("triw", [3, 128, 128])
    mask5_d = k.din("mask5", [128, 640])
    rowm_d = k.din("rowm", [128, 2])
    oc = k.dout("oc", [L, W])

    k.consts(ident_d)
    triw = k.sb("triw_s", [128, 3, 128])
    k.dma('sp', triw[:], triw_d.rearrange("a p n -> p a n"), w=['triw'])
    mask5 = k.sb("mask5_s", [128, 640])
    k.dma('sp', mask5[:], mask5_d, w=['mask5'])
    rowm = k.sb("rowm_s", [128, 2])
    k.dma('sp', rowm[:], rowm_d, w=['rowm'])
    mu1bc = k.bcast_row("mu1bc", mu1, 3 * W)
    vb = [k.bcast_row(f"vb{i}", vecs[i], W) for i in range(7)]
    w0bc, a0bc, kkbc, kabc, rkbc, lngbc, lnbbc = vb
    VK = [f"vb{i}" for i in range(7)]
    muls = k.sb("muls", [128, 3])
    k.dma('sp', muls[:], mul, w=['muls'])
    w2s = k.sb("w2s", [64, W])
    a2s = k.sb("a2s", [64, W])
    k.dma('sp', w2s[:], w2, w=['w2s'])
    k.dma('sp', a2s[:], a2, w=['a2s'])
    g2s = k.sb("g2s", [128, W])
    k.dma('sp', g2s[:], g2, w=['g2s'])
    ST = [k.sb(f"ST{i}", [64, 64], FRC) for i in range(NH)]
    zt = k.sb("zt", [128, W])
    k.memset('dve', zt[:], 0.0, ['zt'])
    for i in range(NH):
        k.cp('dve', ST[i][:], zt[0:64, 0:64], ['zt'], [f'ST{i}'])
    P1s = k.sb("P1s", [128, W], FRC)
    Us = k.sb("Us", [128, W], FRC)
    k.cp('dve', P1s[:], zt[:], ['zt'], ['P1s'])
    k.cp('dve', Us[:], zt[:], ['zt'], ['Us'])

    pt = [k.sb("pt0", [128, 3 * W])] * 2
    pp = [k.sb("pp0", [128, 3 * W])] * 2
    lt = [k.sb("lt0", [128, 3, 128])] * 2
    lp = [k.sb("lp0", [128, 3, 128])] * 2
    k.memset('pool', lt[0][:], 0.0, ['lt0', 'lt1', 'lt2'])
    k.memset('pool', lp[0][:], 0.0, ['lp0', 'lp1', 'lp2', 'lpz'])
    pm2 = [k.sb(f"pm{i_}", [128, 3 * W]) for i_ in range(2)]
    vr2 = [k.sb(f"vr{i_}", [128, W], FR) for i_ in range(2)]
    lm2 = [k.sb(f"lm{i_}", [128, 3, 128]) for i_ in range(2)]
    sw = k.sb("sw", [128, W])
    av = k.sb("av", [128, W])
    gv2 = [k.sb(f"gv{i_}", [128, W]) for i_ in range(2)]
    kkr = k.sb("kkr", [128, W])
    sq = k.sb("sq", [128, W])
    s4 = k.sb("s4", [128, NH])
    rn = k.sb("rn", [128, NH])
    nkk = k.sb("nkk", [128, W])
    kmod = k.sb("kmod", [128, W])
    kka = k.sb("kka", [128, W])
    tmp = k.sb("tmp", [128, W])
    bon2 = [k.sb(f"bon{i_}", [128, NH]) for i_ in range(2)]
    E1 = k.sb("E1", [128, W])
    E2 = k.sb("E2", [128, W])
    E3 = k.sb("E3", [128, W])
    E4 = k.sb("E4", [128, W])
    E1T2 = [k.sb(f"E1T{i_}", [64, NH, 128]) for i_ in range(2)]
    At2 = [k.sb(f"At{i_}", [128, W]) for i_ in range(2)]
    Bs2 = [k.sb(f"Bs{i_}", [128, W]) for i_ in range(2)]
    Ks2 = [k.sb(f"Ks{i_}", [128, W]) for i_ in range(2)]
    Rt2 = [k.sb(f"Rt{i_}", [128, W]) for i_ in range(2)]
    Bfm2 = [[k.sb(f"Bfm{p_}{c}", [128, W]) for c in range(NCK)] for p_ in range(2)]
    Kfm2 = [[k.sb(f"Kfm{p_}{c}", [128, W]) for c in range(NCK)] for p_ in range(2)]
    sqp = k.sb("sqp", [128, W])
    tmpp = k.sb("tmpp", [128, W])
    FT = [k.sb(f"FT{h}", [64, 4, 128], FR) for h in range(NH)]
    A5 = [k.sb(f"A5_{h}", [128, 640], FR) for h in range(NH)]
    NL = [k.sb(f"NL_{h}", [128, 256], FR) for h in range(NH)]
    PQ = [k.sb(f"PQ_{h}", [128, 128], FR) for h in range(NH)]
    W1 = k.sb("W1", [128, W], FR)
    U1 = k.sb("U1", [128, W])
    ysb = k.sb("ysb", [128, W])
    yc = k.sb("yc", [128, W])
    m4 = k.sb("m4", [128, NH])
    r4 = k.sb("r4", [128, NH])
    ot = [k.sb(f"ot{i}", [128, W]) for i in range(2)]
    B = [k.ps(f"psB{i}", [128, 512]) for i in range(8)]
    bk = lambda i: f'psB{i}'
    v3 = lambda t: t.rearrange("p (h j) -> p h j", h=NH)
    bc4 = lambda t: t.unsqueeze(2).broadcast_to([128, NH, 64])


    S0, S1, C0, C1 = 6, 7, 4, 5

    def tile(i):
        b = i % 2
        pm, lm = pm2[b], lm2[b]
        kpm, klm = f'pm{b}', f'lm{b}'
        At, Bs, Ks, Rt, gv, vr, bon, E1T, Bf, Kf = At2[b], Bs2[b], Ks2[b], Rt2[b], gv2[b], vr2[b], bon2[b], E1T2[b], Bfm2[b], Kfm2[b]
        kAt, kBs, kKs, kRt, kgv, kvr, kbon, kE1T, kBf, kKf = (f'{n_}{b}' for n_ in ('At', 'Bs', 'Ks', 'Rt', 'gv', 'vr', 'bon', 'E1T', 'Bf', 'Kf'))
        rows = slice(i * 128, (i + 1) * 128)
        PK, PPK, LTK, LPK = [], [], [], []
        for q in range(3):
            cq = slice(q * W, (q + 1) * W)
            k.dma('sp', pt[b][:, cq], prkv[q][rows, :], w=[f'pt{q}'])
            PK.append(f'pt{q}')
            if i == 0:
                k.dma('sp', pp[b][1:128, cq], prkv[q][0:127, :], w=[f'pp{q}'])
            else:
                k.dma('sp', pp[b][:, cq], prkv[q][i * 128 - 1:i * 128 + 127, :], w=[f'pp{q}'])
            PPK.append(f'pp{q}')
            nr = pls[q].shape[0]
            k.dma('sp', lt[b][0:nr, q, :], pls[q][:, rows], w=[f'lt{q}'])
            LTK.append(f'lt{q}')
            if i == 0:
                k.dma('sp', lp[b][0:nr, q, 1:128], pls[q][:, 0:127], w=[f'lp{q}'])
            else:
                k.dma('sp', lp[b][0:nr, q, :], pls[q][:, i * 128 - 1:i * 128 + 127], w=[f'lp{q}'])
            LPK.append(f'lp{q}')
        if i == 0:
            k.memset('pool', pp[b][0:1, :], 0.0, ['ppz'])
            k.memset('pool', lp[b][:, :, 0:1], 0.0, ['lpz'])
            PPK.append('ppz')
            LPK.append('lpz')
        k.tt('dve', pm[:], pp[b][:], pt[b][:], ALU.subtract, PPK + PK, [kpm])
        k.tt('dve', pm[:], pm[:], mu1bc[:], ALU.mult, [kpm, 'mu1bc'], [kpm])
        k.tt('dve', pm[:], pm[:], pt[b][:], ALU.add, [kpm] + PK, [kpm])
        r_, k_, v_ = pm[:, 0:W], pm[:, W:2 * W], pm[:, 2 * W:3 * W]
        LK = LTK + LPK
        k.tt('dve', lm[:], lp[b][:], lt[b][:], ALU.subtract, LK, [klm])
        for blk in range(3):
            k.stt(lm[:, blk, :], lm[:, blk, :], muls[:, blk:blk + 1], lt[b][:, blk, :], ALU.mult, ALU.add,
                  [klm, 'muls'] + LK, [klm])
        k.act(lm[0:64, 0, :], lm[0:64, 0, :], AF.Tanh, [klm], [klm])
        k.act(lm[:, 2, :], lm[:, 2, :], AF.Sigmoid, [klm], [klm])
        yield 'STAGE'
        k.cp('act', vr[:], v_, [kpm], [kvr])
        k.mm(B[S0][:, 0:W], lm[0:64, 0, :], w2s[:], True, True, [klm, 'w2s'], [bk(S0)])
        k.mm(B[S1][:, 0:W], lm[0:64, 1, :], a2s[:], True, True, [klm, 'a2s'], [bk(S1)])
        yield 'sub'
        k.tt('dve', sw[:], B[S0][:, 0:W], w0bc[:], ALU.add, [bk(S0), VK[0]], ['sw'])
        k.act(sw[:], sw[:], AF.Sigmoid, ['sw'], ['sw'])
        k.tt('dve', av[:], B[S1][:, 0:W], a0bc[:], ALU.add, [bk(S1), VK[1]], ['av'])
        k.act(av[:], av[:], AF.Sigmoid, ['av'], ['av'])
        yield 'sub'
        k.mm(B[S0][:, 0:W], lm[:, 2, :], g2s[:], True, True, [klm, 'g2s'], [bk(S0)])
        k.cp('act', gv[:], B[S0][:, 0:W], [bk(S0)], [kgv])
        yield 'sub'
        k.tt('dve', kkr[:], k_, kkbc[:], ALU.mult, [kpm, VK[2]], ['kkr'])
        k.tt('dve', sq[:], kkr[:], kkr[:], ALU.mult, ['kkr'], ['sq'])
        k.P.op('dve', lambda e: e.tensor_reduce(out=s4[:], in_=v3(sq[:]), axis=AX.X, op=ALU.add), reads=['sq'], writes=['s4'])
        k.act(s4[:], s4[:], AF.Sqrt, ['s4'], ['s4'])
        k.ts('dve', s4[:], s4[:], 1e-12, None, ALU.max, None, ['s4'], ['s4'])
        k.recip(rn[:], s4[:], ['s4'], ['rn'])
        k.ts('dve', rn[:], rn[:], -1.0, None, ALU.mult, None, ['rn'], ['rn'])
        k.tt('dve', v3(nkk[:]), v3(kkr[:]), bc4(rn[:]), ALU.mult, ['kkr', 'rn'], ['nkk'])
        k.stt(tmp[:], av[:], -1.0, kabc[:], ALU.add, ALU.mult, ['av', VK[3]], ['tmp'])
        k.stt(kmod[:], tmp[:], 1.0, k_, ALU.add, ALU.mult, ['tmp', kpm], ['kmod'])
        k.stt(kka[:], nkk[:], -1.0, av[:], ALU.mult, ALU.mult, ['nkk', 'av'], ['kka'])
        k.tt('dve', tmp[:], r_, kmod[:], ALU.mult, [kpm, 'kmod', 'tmp'], ['tmp'])
        k.tt('dve', tmp[:], tmp[:], rkbc[:], ALU.mult, ['tmp', VK[4]], ['tmp'])
        k.P.op('dve', lambda e: e.tensor_reduce(out=bon[:], in_=v3(tmp[:]), axis=AX.X, op=ALU.add), reads=['tmp'], writes=[kbon])
        yield 'sub'
        k.mm(B[S1][:, 0:W], triw[:, 0, :], sw[:], True, True, ['triw', 'sw'], [bk(S1)])
        k.mm(B[S0][:, 0:W], triw[:, 1, :], sw[:], True, True, ['triw', 'sw'], [bk(S0)])
        yield 'sub'
        k.act(E1[:], B[S1][:, 0:W], AF.Exp, [bk(S1)], ['E1'])
        k.act(E2[:], B[S1][:, 0:W], AF.Exp, [bk(S1)], ['E2'], scale=-1.0)
        k.act(E3[:], B[S0][:, 0:W], AF.Exp, [bk(S0)], ['E3'])
        k.mm(B[S1][:, 0:W], triw[:, 2, :], sw[:], True, True, ['triw', 'sw'], [bk(S1)])
        k.act(E4[:], B[S1][:, 0:W], AF.Exp, [bk(S1)], ['E4'])
        yield 'sub'
        for g in range(2):
            for hl in range(4):
                h = 4 * g + hl
                k.mm(B[S0 + g][0:64, hl * 128:(hl + 1) * 128], sw[:, h * 64:(h + 1) * 64], triw[:, 0, :], True, True,
                     ['sw', 'triw'], [bk(S0 + g)])
        yield 'sub'
        for g in range(2):
            k.act(E1T[:, 4 * g:4 * g + 4, :].rearrange("p a t -> p (a t)"), B[S0 + g][0:64, :], AF.Exp, [bk(S0 + g)], [kE1T])
        yield 'sub'
        k.tt('dve', At[:], nkk[:], E3[:], ALU.mult, ['nkk', 'E3'], [kAt])
        k.tt('dve', Bs[:], kka[:], E2[:], ALU.mult, ['kka', 'E2'], [kBs])
        k.tt('dve', Ks[:], kmod[:], E2[:], ALU.mult, ['kmod', 'E2'], [kKs])
        k.tt('dve', Rt[:], r_, E1[:], ALU.mult, [kpm, 'E1'], [kRt])
        for c in range(NCK):
            k.stt(Bf[c][:], kka[:], rowm[:, c:c + 1], E4[:], ALU.mult, ALU.mult, ['kka', 'E4', 'rowm'], [kBf])
            k.stt(Kf[c][:], kmod[:], rowm[:, c:c + 1], E4[:], ALU.mult, ALU.mult, ['kmod', 'E4', 'rowm'], [kKf])
        yield 'STAGE'
        for g in range(1):
            HS = list(range(8))
            for h in HS:
                hl = h
                cs_ = slice(h * 64, (h + 1) * 64)
                for q, (src, key) in enumerate([(At, kAt), (Bs, kBs), (Ks, kKs), (Rt, kRt)]):
                    k.tr(B[hl][0:64, q * 128:(q + 1) * 128], src[:, cs_], k.identf[:], [key], [bk(hl)])
            for h in HS:
                hl = h
                k.cp('act' if h % 2 else 'dve', FT[h][:].rearrange("p a t -> p (a t)"), B[hl][0:64, :], [bk(hl)], [f'FT{h}'])
            for h in HS:
                hl = h
                AtT, BsT, KsT, RtT = (FT[h][:, q, :] for q in range(4))
                k.mm(B[hl][:, 0:128], BsT, AtT, True, True, [f'FT{h}'], [bk(hl)])
                k.mm(B[hl][:, 128:256], AtT, BsT, True, True, [f'FT{h}'], [bk(hl)])
                k.mm(B[hl][:, 256:384], KsT, AtT, True, True, [f'FT{h}'], [bk(hl)])
            for h in HS:
                hl = h
                k.tt('dve', A5[h][:, 0:384], B[hl][:, 0:384], mask5[:, 0:384], ALU.mult, [bk(hl), 'mask5'], [f'A5_{h}'])
            for h in HS:
                hl = h
                AtT, BsT, KsT, RtT = (FT[h][:, q, :] for q in range(4))
                k.mm(B[hl][:, 0:128], BsT, RtT, True, True, [f'FT{h}'], [bk(hl)])
                k.mm(B[hl][:, 128:256], KsT, RtT, True, True, [f'FT{h}'], [bk(hl)])
            for h in HS:
                hl = h
                k.tt('dve', A5[h][:, 384:640], B[hl][:, 0:256], mask5[:, 384:640], ALU.mult, [bk(hl), 'mask5'], [f'A5b_{h}'])
                k.cp('act', NL[h][:], rd(A5[h][:, 0:256]), [f'A5_{h}'], [f'NL_{h}'])
                k.tt('dve', PQ[h][:, 0:128], rd(A5[h][:, 0:128]), k.identf[:], ALU.add, [f'A5_{h}', 'ident'], [f'PQ_{h}'])
            for lev in range(nlev):
                last = (lev == nlev - 1)
                for h in HS:
                    hl = h
                    N_, L_ = NL[h][:, 0:128], NL[h][:, 128:256]
                    k.mm(B[hl][:, 0:128], L_, N_, True, True, [f'NL_{h}'], [bk(hl)])
                    k.mm(B[hl][:, 128:256], N_, L_, True, True, [f'NL_{h}'], [bk(hl)])
                for h in HS:
                    hl = h
                    k.cp('act', NL[h][:], B[hl][:, 0:256], [bk(hl)], [f'NL_{h}'])
                for h in HS:
                    hl = h
                    k.mm(B[hl][:, 256:384], NL[h][:, 128:256], PQ[h][:, 0:128], True, True, [f'NL_{h}', f'PQ_{h}'], [bk(hl)])
                for h in HS:
                    hl = h
                    k.tt('dve', PQ[h][:, 0:128], B[hl][:, 256:384], rd(PQ[h][:, 0:128]), ALU.add, [bk(hl), f'PQ_{h}'], [f'PQ_{h}'])
            yield 'GROUP'
        for h in range(NH):
            k.mm(B[C0][:, h * 64:(h + 1) * 64], A5[h][:, 256:384], vr[:, h * 64:(h + 1) * 64], True, True, [f'A5_{h}', kvr], [bk(C0)])
        yield 'sub'
        k.cp('act', W1[:], B[C0][:, 0:W], [bk(C0)], ['W1'])
        yield 'sub'
        for h in range(NH):
            k.mm(B[C1][:, h * 64:(h + 1) * 64], PQ[h][:, 0:128], W1[:, h * 64:(h + 1) * 64], True, True,
                 [f'PQ_{h}', 'W1'], [bk(C1)])
        yield 'sub'
        k.cp('act', U1[:], B[C1][:, 0:W], [bk(C1)], ['U1'])
        yield 'sub'
        for c in range(NCK):
            cr = slice(c * CH, (c + 1) * CH)
            for h in range(NH):
                k.mm(B[C0][:, h * 64:(h + 1) * 64], FT[h][:, 0, :], ST[h][:], True, True, [f'FT{h}', f'ST{h}'], [bk(C0)])
            yield 'sub'
            k.cp('act', P1s[cr, :], B[C0][cr, 0:W], [bk(C0)], ['P1s'])
            yield 'sub'
            for h in range(NH):
                k.mm(B[C0][:, h * 64:(h + 1) * 64], PQ[h][:, :], P1s[:, h * 64:(h + 1) * 64], True, True,
                     [f'PQ_{h}', 'P1s'], [bk(C0)])
            yield 'sub'
            k.tt('dve', Us[cr, :], B[C0][cr, 0:W], U1[cr, :], ALU.add, [bk(C0), 'U1'], ['Us'])
            yield 'sub'
            for h in range(NH):
                hc_ = slice(h * 64, (h + 1) * 64)
                k.mm(B[C0][:, hc_], FT[h][:, 3, :], ST[h][:], True, False, [f'FT{h}', f'ST{h}'], [bk(C0)])
                k.mm(B[C0][:, hc_], A5[h][:, 384:512], Us[:, hc_], False, False, [f'A5b_{h}', 'Us'], [bk(C0)])
                k.mm(B[C0][:, hc_], A5[h][:, 512:640], vr[:, hc_], False, True, [f'A5b_{h}', kvr], [bk(C0)])
            yield 'sub'
            k.cp('act', ysb[cr, :], B[C0][cr, 0:W], [bk(C0)], ['ysb'])
            for h in range(NH):
                hc_ = slice(h * 64, (h + 1) * 64)
                k.mm(B[C1][0:64, hc_], Bf[c][:, hc_], rdc(Us[:, hc_]), True, False, [kBf, 'Us'], [bk(C1)])
                k.mm(B[C1][0:64, hc_], Kf[c][:, hc_], rd(vr[:, hc_]), False, True, [kKf, kvr], [bk(C1)])
            yield 'sub'
            for h in range(NH):
                hc_ = slice(h * 64, (h + 1) * 64)
                k.stt(ST[h][:], rdc(ST[h][:]), E1T[:, h, (c + 1) * CH - 1:(c + 1) * CH], B[C1][0:64, hc_], ALU.mult, ALU.add,
                      [f'ST{h}', kE1T, bk(C1)], [f'ST{h}'])
        k.P.op('dve', lambda e: e.tensor_reduce(out=m4[:], in_=v3(ysb[:]), axis=AX.X, op=ALU.add), reads=['ysb'], writes=['m4'])
        k.ts('dve', m4[:], m4[:], -1.0 / 64.0, None, ALU.mult, None, ['m4'], ['m4'])
        k.tt('dve', v3(yc[:]), v3(ysb[:]), bc4(m4[:]), ALU.add, ['ysb', 'm4'], ['yc'])
        k.tt('dve', sqp[:], yc[:], yc[:], ALU.mult, ['yc'], ['sqp'])
        k.P.op('dve', lambda e: e.tensor_reduce(out=r4[:], in_=v3(sqp[:]), axis=AX.X, op=ALU.add), reads=['sqp'], writes=['r4'])
        k.ts('dve', r4[:], r4[:], 1.0 / 64.0, GN_EPS, ALU.mult, ALU.add, ['r4'], ['r4'])
        k.act(r4[:], r4[:], AF.Sqrt, ['r4'], ['r4'])
        k.recip(r4[:], r4[:], ['r4'], ['r4'])
        k.tt('dve', v3(yc[:]), v3(yc[:]), bc4(r4[:]), ALU.mult, ['yc', 'r4'], ['yc'])
        k.tt('dve', yc[:], yc[:], lngbc[:], ALU.mult, ['yc', VK[5]], ['yc'])
        k.tt('dve', yc[:], yc[:], lnbbc[:], ALU.add, ['yc', VK[6]], ['yc'])
        k.tt('dve', v3(tmpp[:]), v3(rd(vr[:])), bc4(bon[:]), ALU.mult, [kvr, kbon], ['tmpp'])
        k.tt('dve', yc[:], yc[:], tmpp[:], ALU.add, ['yc', 'tmpp'], ['yc'])
        k.tt('dve', ot[b][:], yc[:], gv[:], ALU.mult, ['yc', kgv], [f'ot{b}'])
        k.dma('pool', oc[rows, :], ot[b][:], r=[f'ot{b}'], final=True)

    gens = {}
    done = set()

    def adv(j):
        try:
            return next(gens[j])
        except StopIteration:
            done.add(j)
            return 'END'

    for step in range(NT + 2):
        if step < NT:
            gens[step] = tile(step)
            while adv(step) != 'STAGE':
                pass
        jb = step - 2
        if 0 <= jb < NT:
            n_g = 0
            while n_g < 1:
                if adv(jb) == 'GROUP':
                    n_g += 1
        ja = step - 1
        a_live = 0 <= ja < NT
        b_live = 0 <= jb < NT
        while a_live or b_live:
            if a_live:
                if adv(ja) == 'STAGE':
                    a_live = False
            if b_live:
                if adv(jb) == 'END':
                    b_live = False
    return k.finish()


def rwkv_consts(CH=64):
    c = -math.exp(-0.5)
    blk = np.kron(np.eye(128 // CH), np.ones((CH, CH)))
    s_idx = np.arange(128)[:, None]
    t_idx = np.arange(128)[None, :]
    triw = np.stack([c * blk * (s_idx <= t_idx), c * blk * (s_idx < t_idx), c * blk * (s_idx > t_idx)]).astype(np.float32)
    lt_, le_, gt_ = blk * (s_idx < t_idx), blk * (s_idx <= t_idx), blk * (t_idx < s_idx)
    mask5 = np.concatenate([lt_, gt_, lt_, le_, le_], 1).astype(np.float32)
    rowm = np.stack([(np.arange(128) < 64), (np.arange(128) >= 64)], 1).astype(np.float32) if CH == 64 else np.ones((128, 2), np.float32)
    return dict(ident=np.eye(128, dtype=np.float32), triw=triw, mask5=mask5, rowm=rowm)


def rwkv_host_inputs(s, p_rwkv, prm, NH=4, CH=64):
    L = p_rwkv.shape[0]
    cs = slice(64 * NH * s, 64 * NH * (s + 1))
    r_, w1, k_, v_, a1, g1 = np.split(p_rwkv, np.cumsum([512, 64, 512, 512, 64])[:5], axis=-1)
    mu = prm['rwkv_mu']
    mur, muw1, muk, muv, mua1, mug1 = np.split(mu, np.cumsum([512, 64, 512, 512, 64])[:5])
    zm = np.zeros(64, np.float32)
    mul = np.concatenate([muw1, zm, mua1, zm, mug1]).reshape(3, 128).T
    vecs = np.stack([prm['rwkv_w0'][cs], prm['rwkv_a0'][cs], prm['rwkv_k_k'][cs], prm['rwkv_k_a'][cs],
                     prm['rwkv_r_k'].reshape(-1)[cs], prm['rwkv_ln_gain'][cs], prm['rwkv_ln_bias'][cs]])
    c_ = np.ascontiguousarray
    d = dict(pr=c_(r_[:, cs]), pk=c_(k_[:, cs]), pv=c_(v_[:, cs]),
             mu1=c_(np.concatenate([mur[cs], muk[cs], muv[cs]])),
             plw=c_(w1.T), pla=c_(a1.T), plg=c_(g1.T), mul=c_(mul),
             w2=c_(prm['rwkv_w2'][:, cs]), a2=c_(prm['rwkv_a2'][:, cs]),
             g2=c_(prm['rwkv_g2'][:, cs]), vecs=c_(vecs))
    d.update(rwkv_consts(CH))
    return d


FM0 = [(0, 128, 0), (128, 128, 128), (256, 128, 256), (384, 128, 384), (1536, 16, 512)] + \
      [(1552 + j * 128, 128, 528 + j * 128) for j in range(4)]
NF0 = 1040
FM1 = [(512, 64, 0), (1600, 64, 64), (1664, 128, 128)] + [(1792 + j * 128, 128, 256 + j * 128) for j in range(8)]
NF1 = 1280


def host_params(inp):
    c_ = lambda a: np.ascontiguousarray(np.asarray(a), dtype=np.float32)
    P = {}
    P['ident'] = np.eye(128, dtype=np.float32)
    P['triu'] = np.triu(np.ones((128, 128), np.float32))
    P['trigt'] = np.tril(np.ones((128, 128), np.float32), -1)
    for l in range(2):
        for j in range(7):
            P[f'g{l}_{j}'] = c_(inp['norm_gain'][l][j])
        for nm in ('xa_wq', 'xa_wk', 'xa_wv', 'xa_wo', 'mlp_w1', 'mlp_w2'):
            P[f'{nm}{l}'] = c_(inp[nm][l])
    P['w_in0'] = c_(inp['ab_w_in'][0])
    P['w_in1'] = c_(inp['cd_w_in'][0])
    P['w_out0'] = c_(inp['ab_w_out'][0])
    P['w_out1'] = c_(inp['cd_w_out'][0])
    P['wglu'] = c_(inp['s5_w_glu'][0])
    P['bglu'] = c_(inp['s5_b_glu'][0])
    prm0 = {k_: np.asarray(inp[k_][0]) for k_ in inp if k_.startswith('s5_') or k_.startswith('gla_')}
    prm1 = {k_: np.asarray(inp[k_][0]) for k_ in inp if k_.startswith('rwkv_') or k_.startswith('lru_')}
    for s in range(2):
        cs = slice(s * 128, (s + 1) * 128)
        P[f'gla_w2_{s}'] = c_(prm0['gla_w_decay2'][:, cs])
        P[f'gla_bd_{s}'] = c_(prm0['gla_b_decay'][None, cs])
        P[f'gla_gn_{s}'] = c_(prm0['gla_norm_gain'][2 * s:2 * s + 2].reshape(256))
        d = s5_host_inputs(s, np.zeros((2, 512), np.float32), prm0)
        for nm in ('lam_re', 'lam_im', 'lstep', 'Bre', 'Bim', 'Cre', 'Cim', 'dsk'):
            P[f's5_{nm}_{s}'] = c_(d[nm])
        P['iota_p'] = c_(d['iota_p'])
        P['iota_f'] = c_(d['iota_f'])
        if s == 0:
            d = rwkv_host_inputs(0, np.zeros((2, 1792), np.float32), prm1, 8, 64)
            for nm in ('mu1', 'mul', 'w2', 'a2', 'g2', 'vecs'):
                P[f'rw_{nm}'] = c_(d[nm])
            for nm in ('triw', 'mask5', 'rowm'):
                P[f'rw_{nm}'] = c_(d[nm])
        d = lru_host_inputs(s, np.zeros((2, 512), np.float32), np.zeros((2, 512), np.float32), prm1)
        for nm in ('cw', 'cb', 'Wa', 'Wx', 'ba', 'bx', 'lam'):
            P[f'lru_{nm}_{s}'] = c_(d[nm])
    return P


def build_fused(P, L):
    k = K(fused=True)
    X = {nm: k.xin(nm, a.shape) for nm, a in P.items()}
    x = k.xin('x', [L, D])
    mem = k.xin('mem', [256, D])
    out = k.xout('out', [L, D])
    proj0 = k.scratch('proj0', [L, 2064])
    PT0 = k.scratch('PT0', [NF0, L])
    proj1 = k.scratch('proj1', [L, 2816])
    PT1 = k.scratch('PT1', [NF1, L])
    o = k.scratch('o', [L, D])
    odT = k.scratch('odT', [512, L])
    h1 = k.scratch('h1', [L, D])
    h2 = k.scratch('h2', [L, D])
    h3 = k.scratch('h3', [L, D])

    def cblock(l, hin, hout, glu, ob_fm):
        io = dict(oa=o[:, 0:512], hin=hin, wout=X[f'w_out{l}'], g1=X[f'g{l}_1'], ident=X['ident'], hout=h1)
        if ob_fm:
            io['obT'] = odT
        else:
            io['ob'] = o[:, 512:1024]
        if glu:
            io.update(wglu=X['wglu'], bglu=X['bglu'])
        k.begin_phase(f'C1_{l}', io)
        build_C1(L, glu, k=k, ob_fm=ob_fm)
        k.begin_phase(f'C2_{l}', dict(hin=h1, mem=mem, wq=X[f'xa_wq{l}'], wk=X[f'xa_wk{l}'], wv=X[f'xa_wv{l}'], wo=X[f'xa_wo{l}'],
                                      g2=X[f'g{l}_2'], g3=X[f'g{l}_3'], g6=X[f'g{l}_6'], ident=X['ident'], hout=h2))
        build_C2(L, k=k)
        k.begin_phase(f'C3_{l}', dict(hin=h2, w1=X[f'mlp_w1{l}'], w2=X[f'mlp_w2{l}'], g4=X[f'g{l}_4'], g5=X[f'g{l}_5'],
                                      ident=X['ident'], hout=hout))
        build_C3(L, k=k)

    k.begin_phase('A0', dict(x=x, gain=X['g0_0'], W=X['w_in0'], ident=X['ident'], out=proj0, outT=PT0))
    build_A2(L, 2064, FM0, NF0, k=k)
    for s in range(2):
        io_g = dict(qT=PT0[s * 128:(s + 1) * 128, :], kT=PT0[256 + s * 128:256 + (s + 1) * 128, :],
                    ktok=proj0[:, 256 + s * 128:256 + (s + 1) * 128], v=proj0[:, 512 + s * 256:512 + (s + 1) * 256],
                    gate=proj0[:, 1024 + s * 256:1024 + (s + 1) * 256], dlrT=PT0[512:528, :],
                    w2=X[f'gla_w2_{s}'], bdec=X[f'gla_bd_{s}'], gn=X[f'gla_gn_{s}'], triu=X['triu'],
                    trigt=X['trigt'], oa=o[:, s * 256:(s + 1) * 256])
        k.begin_phase(f'GLA{s}', io_g)
        build_GLA(L, k=k)
    for s in range(2):
        io_s = dict(uT=PT0[528 + s * 256:528 + (s + 1) * 256, :], u=proj0[:, 1552 + s * 256:1552 + (s + 1) * 256],
                    triu=X['triu'], iota_p=X['iota_p'], iota_f=X['iota_f'], y=o[:, 512 + s * 256:512 + (s + 1) * 256])
        for nm in ('lam_re', 'lam_im', 'lstep', 'Bre', 'Bim', 'Cre', 'Cim', 'dsk'):
            io_s[nm] = X[f's5_{nm}_{s}']
        k.begin_phase(f'S5{s}', io_s)
        build_S5(L, k=k)
    cblock(0, x, h3, True, False)
    k.begin_phase('A1', dict(x=h3, gain=X['g1_0'], W=X['w_in1'], ident=X['ident'], out=proj1, outT=PT1))
    build_A2(L, 2816, FM1, NF1, k=k)
    io = dict(pr=proj1[:, 0:512], pk=proj1[:, 576:1088], pv=proj1[:, 1088:1600], plw=PT1[0:64, :], pla=PT1[64:128, :],
              plg=PT1[128:256, :], ident=X['ident'], triw=X['rw_triw'], mask5=X['rw_mask5'], rowm=X['rw_rowm'], oc=o[:, 0:512])
    for nm in ('mu1', 'mul', 'w2', 'a2', 'g2', 'vecs'):
        io[nm] = X[f'rw_{nm}']
    k.begin_phase('RW', io)
    build_RWKVP(L, k=k, CH=64)
    streams = []
    for s in range(2):
        io = dict(xbT=PT1[256 + s * 256:256 + (s + 1) * 256, :], gateT=PT1[768 + s * 256:768 + (s + 1) * 256, :],
                  odT=odT[s * 256:(s + 1) * 256, :])
        for nm in ('cw', 'cb', 'Wa', 'Wx', 'ba', 'bx', 'lam'):
            io[nm] = X[f'lru_{nm}_{s}']
        streams.append((f'l{s}_', io, lambda kk: gen_LRU(L, kk)))
    k.begin_phase('LRU', {})
    run_streams(k, streams)
    k.finish()
    cblock(1, h3, out, False, True)
    return k.finish_program()


BATCH, SEQ = 4, 4096
_CACHE = {}


def kernel(**inp):
    inp = {k_: np.asarray(v_) for k_, v_ in inp.items()}
    P = host_params(inp)
    if 'nc' not in _CACHE:
        _CACHE['nc'] = build_fused(P, SEQ)
    nc = _CACHE['nc']
    maps = []
    for b in range(BATCH):
        m = dict(P)
        m['x'] = np.ascontiguousarray(inp['x'][b], dtype=np.float32)
        m['mem'] = np.ascontiguousarray(inp['mem'][b], dtype=np.float32)
        maps.append(m)
    res = run_bass_kernel_spmd(nc, maps, core_ids=list(range(BATCH))).results
    return np.ascontiguousarray(np.stack([res[b]['out'] for b in range(BATCH)]).astype(np.float32))
```

```python
import os
import math
from contextlib import ExitStack


import numpy as np
import concourse.bass as bass
import concourse.mybir as mybir
from concourse.bass_utils import run_bass_kernel_spmd

F32 = mybir.dt.float32
BF16 = mybir.dt.bfloat16
I32 = mybir.dt.int32
AF = mybir.ActivationFunctionType
ALU = mybir.AluOpType
AX = mybir.AxisListType

ENGS = ['pe', 'act', 'dve', 'pool', 'sp']
NDMA_SLOTS = 8
SAME_ENGINE_SYNC = os.environ.get("NOSELF", "0") != "1"


class Prog:
    def __init__(self, nc):
        self.nc = nc
        self.ops = {e: [] for e in ENGS}
        self.cnt = {e: 0 for e in ENGS}
        self.last_w = {}
        self.readers = {}
        self.seen = {e: {} for e in ENGS}
        self.dma_n = {e: 0 for e in ENGS}
        self.dma_tok = {e: [None] * NDMA_SLOTS for e in ENGS}
        self.final_tokens = []
        from contextlib import ExitStack
        self.sem_stack = ExitStack()
        self.sems = {}
        for e in ['pe', 'act', 'dve', 'pool']:
            self.sems[('c', e)] = self.sem_stack.enter_context(nc.semaphore("s_c_" + e))
        for q in ['sp', 'pool']:
            for sl in range(NDMA_SLOTS):
                self.sems[('d', q, sl)] = self.sem_stack.enter_context(nc.semaphore(f"s_d_{q}_{sl}"))

    def barrier(self):
        toks = []
        for e in ['pe', 'act', 'dve', 'pool']:
            if self.cnt[e] > 0:
                toks.append((('c', e), self.cnt[e]))
        for q in ENGS:
            for t in self.dma_tok[q]:
                if t is not None:
                    toks.append(t)
        for e in ENGS:
            waits = []
            for (sem, val) in toks:
                if sem == ('c', e):
                    continue
                if self.seen[e].get(sem, 0) >= val:
                    continue
                waits.append((sem, val))
                self.seen[e][sem] = val
            if waits:
                self.ops[e].append((waits, None, None))
        self.last_w = {}
        self.readers = {}

    def _deps(self, eng, reads, writes):
        toks = []
        for r in reads:
            t = self.last_w.get(r)
            if t is not None:
                toks.append(t)
        for w in writes:
            t = self.last_w.get(w)
            if t is not None:
                toks.append(t)
            toks.extend(self.readers.get(w, []))
        need = {}
        for (sem, val) in toks:
            if not SAME_ENGINE_SYNC and sem == ('c', eng):
                continue
            if sem == ('c', 'pe') and eng == 'pe':
                continue
            if self.seen[eng].get(sem, 0) >= val:
                continue
            if need.get(sem, 0) < val:
                need[sem] = val
        for sem, val in need.items():
            self.seen[eng][sem] = val
        return list(need.items())

    def _commit(self, tok, reads, writes):
        for w in writes:
            self.last_w[w] = tok
            self.readers[w] = []
        for r in reads:
            if r in writes:
                continue
            self.readers.setdefault(r, []).append(tok)

    def op(self, eng, fn, reads=(), writes=()):
        self.nrec = getattr(self, 'nrec', 0) + 1
        if self.nrec > int(os.environ.get("MAXOPS", "100000000")):
            return None
        kp = getattr(self, 'key_prefix', '')
        reads = [r if r.startswith('ps') else kp + r for r in reads]
        writes = [w if w.startswith('ps') else kp + w for w in writes]
        pk = getattr(self, 'ps_prefix', '')
        reads = [('ps' + pk + r[2:]) if r.startswith('ps') else r for r in reads]
        writes = [('ps' + pk + w[2:]) if w.startswith('ps') else w for w in writes]
        writes = list(writes) + [r for r in reads if r.startswith('ps') and r not in writes]
        waits = self._deps(eng, reads, writes)
        self.cnt[eng] += 1
        tok = (('c', eng), self.cnt[eng])
        self.ops[eng].append((waits, fn, tok))
        self._commit(tok, reads, writes)
        return tok

    def dma(self, q, out, in_, reads=(), writes=(), final=False, **kw):
        self.nrec = getattr(self, 'nrec', 0) + 1
        if self.nrec > int(os.environ.get("MAXOPS", "100000000")):
            return None
        kp = getattr(self, 'key_prefix', '')
        reads = [kp + r for r in reads]
        writes = [kp + w for w in writes]
        waits = self._deps(q, reads, writes)
        n = self.dma_n[q]
        slot = n % NDMA_SLOTS
        prev = self.dma_tok[q][slot]
        if prev is not None and self.seen[q].get(prev[0], 0) < prev[1]:
            waits.append(prev)
            self.seen[q][prev[0]] = prev[1]
        tok = (('d', q, slot), 16 * (n // NDMA_SLOTS + 1))
        self.dma_n[q] += 1
        self.dma_tok[q][slot] = tok

        def fn(e, out=out, in_=in_, kw=kw):
            return e.dma_start(out=out, in_=in_, **kw)
        self.ops[q].append((waits, fn, tok))
        self._commit(tok, reads, writes)
        if final:
            self.final_tokens.append(tok)
        return tok

    def emit(self, last=True):
        nc = self.nc
        sems = self.sems
        with nc.Block() as block:
            final = list(self.final_tokens) if last else []

            def run(e, name):
                for waits, fn, tok in self.ops[name]:
                    for (s, v) in waits:
                        e.wait_ge(sems[s], v)
                    if fn is None:
                        continue
                    inst = fn(e)
                    inc = 16 if tok[0][0] == 'd' else 1
                    inst.then_inc(sems[tok[0]], inc)
                if name == 'sp':
                    for (s, v) in final:
                        e.wait_ge(sems[s], v)
                self.ops[name] = []

            @block.tensor
            def _(e):
                run(e, 'pe')

            @block.scalar
            def _(e):
                run(e, 'act')

            @block.vector
            def _(e):
                run(e, 'dve')

            @block.gpsimd
            def _(e):
                run(e, 'pool')

            @block.sync
            def _(e):
                run(e, 'sp')
        if last:
            self.sem_stack.close()


D = 1024
KC = 8
EPS = 1e-6


class K:
    def __init__(self, fused=False):
        self.nc = bass.Bass("TRN2", target_bir_lowering=False)
        self.st = ExitStack()
        self.P = Prog(self.nc)
        self.n = 0
        self.fused = fused
        self.io = {}
        self.pfx = ""

    def begin_phase(self, name, io):
        self.pfx = name + "_"
        self.io = io
        self.st = ExitStack()
        for a in ('wstage', 'rr_cache', 'identf', 'identb'):
            if hasattr(self, a):
                delattr(self, a)

    def scratch(self, name, shape, dt=F32):
        return self.nc.dram_tensor(name, list(shape), dt, kind="Internal").ap()

    def xin(self, name, arr_shape, dt=F32):
        return self.nc.dram_tensor(name, list(arr_shape), dt, kind="ExternalInput").ap()

    def xout(self, name, arr_shape, dt=F32):
        return self.nc.dram_tensor(name, list(arr_shape), dt, kind="ExternalOutput").ap()

    def din(self, name, shape, dt=F32):
        if self.fused:
            ap = self.io[name]
            assert list(ap.shape) == list(shape), (name, ap.shape, shape)
            return ap
        return self.nc.dram_tensor(name, list(shape), dt, kind="ExternalInput").ap()

    def dout(self, name, shape, dt=F32):
        if self.fused:
            ap = self.io[name]
            assert list(ap.shape) == list(shape), (name, ap.shape, shape)
            return ap
        return self.nc.dram_tensor(name, list(shape), dt, kind="ExternalOutput").ap()

    def sb(self, name, shape, dt=F32):
        pers = getattr(self, 'persist', None)
        if pers is not None and (self.pfx + name) in pers:
            return pers[self.pfx + name]
        return self.st.enter_context(self.nc.sbuf_tensor(self.pfx + name, list(shape), dt))

    def push_scope(self, persistent):
        self.persist = getattr(self, 'persist', None) or {}
        for (name, shape, dt) in persistent:
            self.persist[self.pfx + name] = self.st.enter_context(self.nc.sbuf_tensor(self.pfx + name, list(shape), dt))
        self._st_saved = self.st
        self.st = ExitStack()

    def pop_scope(self):
        self.P.barrier()
        self.P.emit(last=False)
        self.st.close()
        self.st = self._st_saved

    def ps(self, name, shape, dt=F32):
        return self.st.enter_context(self.nc.psum_tensor(self.pfx + name, list(shape), dt))

    def finish(self, last=True):
        if self.fused:
            self.P.barrier()
            self.P.emit(last=False)
            self.st.close()
            return None
        self.P.emit()
        self.st.close()
        return self.nc

    def finish_program(self):
        self.P.emit(last=True)
        return self.nc

    def mm(self, out, lhsT, rhs, start, stop, r, w):
        self.P.op('pe', lambda e: e.matmul(out, lhsT=lhsT, rhs=rhs, start=start, stop=stop), reads=r, writes=w)

    def tr(self, out, in_, ident, r, w):
        self.P.op('pe', lambda e: e.transpose(out=out, in_=in_, identity=ident), reads=list(r) + ['ident'], writes=w)

    def act(self, out, in_, func, r, w, **kw):
        self.P.op('act', lambda e: e.activation(out=out, in_=in_, func=func, **kw), reads=r, writes=w)

    def tt(self, eng, out, in0, in1, op, r, w):
        self.P.op(eng, lambda e: e.tensor_tensor(out=out, in0=in0, in1=in1, op=op), reads=r, writes=w)

    def ts(self, eng, out, in0, s1, s2, op0, op1, r, w):
        if op1 is None:
            self.P.op(eng, lambda e: e.tensor_scalar(out=out, in0=in0, scalar1=s1, scalar2=None, op0=op0), reads=r, writes=w)
        else:
            self.P.op(eng, lambda e: e.tensor_scalar(out=out, in0=in0, scalar1=s1, scalar2=s2, op0=op0, op1=op1), reads=r, writes=w)

    def stt(self, out, in0, scalar, in1, op0, op1, r, w):
        self.P.op('dve', lambda e: e.scalar_tensor_tensor(out=out, in0=in0, scalar=scalar, in1=in1, op0=op0, op1=op1),
                  reads=r, writes=w)

    def cp(self, eng, out, in_, r, w):
        if eng == 'act':
            self.P.op('act', lambda e: e.copy(out=out, in_=in_), reads=r, writes=w)
        else:
            self.P.op(eng, lambda e: e.tensor_copy(out=out, in_=in_), reads=r, writes=w)

    def recip(self, out, in_, r, w):
        self.P.op('dve', lambda e: e.reciprocal(out=out, in_=in_), reads=r, writes=w)

    def memset(self, eng, ap, val, w):
        self.P.op(eng, lambda e: e.memset(ap, val), reads=[], writes=w)

    def dma(self, q, out, in_, r=(), w=(), final=False, **kw):
        self.P.dma(q, out, in_, reads=r, writes=w, final=final, **kw)

    def consts(self, ident_d):
        self.identf = self.sb("identf", [128, 128], F32)
        self.identb = self.sb("identb", [128, 128], BF16)
        self.dma('sp', self.identf[:], ident_d, w=['ident'])
        self.cp('dve', self.identb[:], self.identf[:], ['ident'], ['ident'])

    def gain_cols(self, name, g_d):
        t = self.sb(name, [128, KC], F32)
        self.dma('sp', t[:], g_d.rearrange("(kc p) -> p kc", p=128), w=[name], allow_slow_non_contiguous=True)
        return t

    def bcast_row(self, name, vec_d, n):
        t = self.sb(name, [128, n], F32)
        self.dma('sp', t[:], vec_d.partition_broadcast(128), w=[name])
        return t

    def load_weight(self, name, w_d, kchunks, ncols, gcol=None, gkey=None, stage_cols=2048, q='sp', chunk_keys=False):
        wb = self.sb(name, [128, kchunks, ncols], BF16)
        if not hasattr(self, 'wstage'):
            self.wstage = [self.sb(f"wstage{i}", [128, stage_cols], F32) for i in range(2)]
            self.wstage_n = 0
            self.wstage_cols = stage_cols
        sc = self.wstage_cols
        wv = w_d.rearrange("(kc p) n -> p kc n", p=128)
        order = [(kc, c0) for kc in range(kchunks) for c0 in range(0, ncols, sc)]
        if chunk_keys:
            order = [(kc, c0) for c0 in range(0, ncols, sc) for kc in range(kchunks)]
        for (kc, c0) in order:
            if True:
                cw = min(sc, ncols - c0)
                wkey = f'{name}{kc}_{c0 // sc}' if chunk_keys else f'{name}{kc}'
                b = self.wstage_n % 2
                self.wstage_n += 1
                stg = self.wstage[b]
                self.dma(q, stg[:, 0:cw], wv[:, kc, c0:c0 + cw], w=[f'wstage{b}'])
                eng = 'act' if (kc % 2 == 0) else 'dve'
                if gcol is not None:
                    if eng == 'act':
                        self.act(wb[:, kc, c0:c0 + cw], stg[:, 0:cw], AF.Copy, [f'wstage{b}', gkey], [wkey],
                                 scale=gcol[:, kc:kc + 1])
                    else:
                        self.ts('dve', wb[:, kc, c0:c0 + cw], stg[:, 0:cw], gcol[:, kc:kc + 1], None, ALU.mult, None,
                                [f'wstage{b}', gkey], [wkey])
                else:
                    self.cp(eng, wb[:, kc, c0:c0 + cw], stg[:, 0:cw], [f'wstage{b}'], [wkey])
        return wb

    def rstd_of(self, x_ap, xkey, ss, rstd, junk, key, ncols=D):
        self.act(junk, x_ap, AF.Square, [xkey], ['junk', key + 'ss'], accum_out=ss)
        self.ts('dve', rstd, ss, 1.0 / ncols, EPS, ALU.mult, ALU.add, [key + 'ss'], [key])
        self.act(rstd, rstd, AF.Sqrt, [key], [key])
        self.recip(rstd, rstd, [key], [key])


def pipeline(make_gen, n):
    active = []
    for i in range(n):
        for g in list(active):
            try:
                next(g)
            except StopIteration:
                active.remove(g)
        g = make_gen(i)
        active.append(g)
        try:
            next(g)
        except StopIteration:
            active.remove(g)
    while active:
        for g in list(active):
            try:
                next(g)
            except StopIteration:
                active.remove(g)


def pipeline_gen(make_gen, n):
    active = []
    for i in range(n):
        for g in list(active):
            try:
                next(g)
            except StopIteration:
                active.remove(g)
        g = make_gen(i)
        active.append(g)
        try:
            next(g)
        except StopIteration:
            active.remove(g)
        yield
    while active:
        for g in list(active):
            try:
                next(g)
            except StopIteration:
                active.remove(g)
        yield


def run_streams(k, streams):
    base_pfx = k.pfx
    gens = []
    for (pf, io, gf) in streams:
        gens.append([pf, io, None, gf])
    active = list(gens)
    while active:
        for st in list(active):
            pf, io, g, gf = st
            k.pfx = base_pfx + pf
            k.P.key_prefix = pf
            k.P.ps_prefix = pf
            k.io = io
            try:
                if g is None:
                    st[2] = gf(k)
                    g = st[2]
                next(g)
            except StopIteration:
                active.remove(st)
    k.pfx = base_pfx
    k.P.key_prefix = ''
    k.P.ps_prefix = ''


GELU_C = 1.5957691216057308


def norm_T(k, xt, xkey, xn, xnkey, xT_dst, xTkey, psT, psTkey, ss, rstd, junk, key, evac_eng='act'):
    k.rstd_of(xt, xkey, ss, rstd, junk, key)
    k.ts('dve', xn, xt, rstd, None, ALU.mult, None, [xkey, key], [xnkey])
    for kc in range(KC):
        k.tr(psT[:, kc * 128:(kc + 1) * 128], xn[:, kc * 128:(kc + 1) * 128], k.identb[:], [xnkey], [psTkey])
    k.cp(evac_eng, xT_dst, psT[:].rearrange("p (k t) -> p k t", k=KC), [psTkey], [xTkey])


def post_norm_res(k, ps2, pskeys, ht, hkey, gbc, gkey, tmp2, tmpkeys, ss2, rstd, junk, key):
    for j in range(2):
        k.act(junk[:, 0:512], ps2[j], AF.Square, [pskeys[j]], ['junk', key + f'ss{j}'], accum_out=ss2[:, j:j + 1])
    k.tt('dve', ss2[:, 0:1], ss2[:, 0:1], ss2[:, 1:2], ALU.add, [key + 'ss0', key + 'ss1'], [key + 'ss0'])
    k.ts('dve', rstd, ss2[:, 0:1], 1.0 / D, EPS, ALU.mult, ALU.add, [key + 'ss0'], [key])
    k.act(rstd, rstd, AF.Sqrt, [key], [key])
    k.recip(rstd, rstd, [key], [key])
    for j in range(2):
        sl = slice(j * 512, (j + 1) * 512)
        k.stt(tmp2[j], ps2[j], rstd, gbc[:, sl], ALU.mult, ALU.mult, [pskeys[j], key, gkey], [tmpkeys[j]])
        k.tt('pool', ht[:, sl], ht[:, sl], tmp2[j], ALU.add, [tmpkeys[j], hkey], [hkey])


def build_C1(NTOK, glu, k=None, ob_fm=False):
    k = k or K()
    NT = NTOK // 128
    oa = k.din("oa", [NTOK, 512])
    if ob_fm:
        obT = k.din("obT", [512, NTOK])
    else:
        ob = k.din("ob", [NTOK, 512])
    hin = k.din("hin", [NTOK, D])
    wout = k.din("wout", [D, D])
    g1 = k.din("g1", [D])
    ident_d = k.din("ident", [128, 128])
    if glu:
        wglu = k.din("wglu", [512, 512])
        bglu = k.din("bglu", [512])
    hout = k.dout("hout", [NTOK, D])
    k.consts(ident_d)
    g1bc = k.bcast_row("g1bc", g1, D)
    Wout = k.load_weight("Wout", wout, KC, D, stage_cols=1024)
    if glu:
        Wglu = k.load_weight("Wglu", wglu, 4, 512)
        bgbc = k.bcast_row("bgbc", bglu, 512)

    def ring(nm, shape, n, dt=F32):
        return [k.sb(f"{nm}{j}", shape, dt) for j in range(n)]
    oc = ring("oc", [128, D], 10 if glu else 4)
    ocb = ring("ocb", [128, D], 3, BF16)
    oT = ring("oT", [128, KC, 128], 3, BF16)
    ht = ring("ht", [128, D], 4)
    mix = ring("mix", [128, D], 5)
    tmp = ring("tmp", [128, D], 3)
    ss2 = ring("ss2", [128, 2], 4)
    rstd = ring("rstd", [128, 1], 5)
    junk = k.sb("junk", [128, D], BF16)
    if ob_fm:
        obt = ring("obt", [128, 4, 128], 4)
    if glu:
        yb = ring("yb", [128, 512], 3, BF16)
        yT = ring("yT", [128, 4, 128], 3, BF16)
        t1 = ring("t1", [128, 512], 9)
        zs = ring("zs", [128, 512], 4)
        psTg = k.ps("psTg", [128, D], BF16)
        psG = k.ps("psG", [128, 512])
    psTm = [k.ps(f"psTm{j}", [128, D], BF16) for j in range(2)]
    psM = [k.ps(f"psM{j}", [128, 512]) for j in range(4)]

    def tile(i):
        rows = slice(i * 128, (i + 1) * 128)
        def T(lst, nm):
            j = i % len(lst)
            return lst[j], f'{nm}{j}'
        oc_, koc = T(oc, 'oc'); ocb_, kocb = T(ocb, 'ocb'); oT_, koT = T(oT, 'oT'); ht_, kht = T(ht, 'ht')
        mix_, kmix = T(mix, 'mix'); tmp_, ktmp = T(tmp, 'tmp'); ss_, kss = T(ss2, 'ss2'); rs_, krs = T(rstd, 'rstd')
        pm = [psM[2 * (i % 2)], psM[2 * (i % 2) + 1]]
        kpm = [f'psM{2 * (i % 2)}', f'psM{2 * (i % 2) + 1}']
        ptm, kptm = psTm[i % 2], f'psTm{i % 2}'
        kA, kB = koc + 'A', koc + 'B'
        k.dma('sp', oc_[:, 0:512], oa[rows, :], w=[kA])
        if ob_fm:
            obt_, kobt = T(obt, 'obt')
            k.dma('sp', obt_[:], obT[:, rows].rearrange("(a p) t -> p a t", p=128), w=[kobt])
        else:
            k.dma('sp', oc_[:, 512:1024], ob[rows, :], w=[kB])
        yield
        if glu:
            y = oc_[:, 512:1024]
            yb_, kyb = T(yb, 'yb'); yT_, kyT = T(yT, 'yT'); t1_, kt1 = T(t1, 't1'); zs_, kzs = T(zs, 'zs')
            k.cp('dve', yb_[:], y, [kB], [kyb])
            k.act(t1_[:], y, AF.Square, [kB], [kt1])
            k.act(t1_[:], t1_[:], AF.Copy, [kt1], [kt1], scale=0.044715, bias=1.0)
            yield
            for kc in range(4):
                k.tr(psTg[:, kc * 128:(kc + 1) * 128], yb_[:, kc * 128:(kc + 1) * 128], k.identb[:], [kyb], ['psTg'])
            k.tt('pool', t1_[:], t1_[:], y, ALU.mult, [kt1, kB], [kt1])
            yield
            k.cp('act', yT_[:], psTg[:, 0:512].rearrange("p (k t) -> p k t", k=4), ['psTg'], [kyT])
            k.act(t1_[:], t1_[:], AF.Sigmoid, [kt1], [kt1], scale=GELU_C)
            yield
            for kc in range(4):
                k.mm(psG[:], yT_[:, kc, :], Wglu[:, kc, :], kc == 0, kc == 3, [kyT, f'Wglu{kc}'], ['psG'])
            yield
            k.tt('dve', zs_[:], psG[:], bgbc[:], ALU.add, ['psG', 'bgbc'], [kzs])
            yield
            k.act(zs_[:], zs_[:], AF.Sigmoid, [kzs], [kzs])
            yield
            k.tt('dve', zs_[:], t1_[:], zs_[:], ALU.mult, [kt1, kzs], [kzs])
            k.tt('dve', y, y, zs_[:], ALU.mult, [kB, kzs], [kB])
        if ob_fm:
            k.cp('dve', ocb_[:, 0:512], oc_[:, 0:512], [kA], [kocb])
            k.cp('pool', oT_[:, 4:8, :], obt_[:], [kobt], [koT + 'b'])
        else:
            k.cp('dve', ocb_[:], oc_[:], [kA, kB], [kocb])
        yield
        nk = 4 if ob_fm else KC
        for kc in range(nk):
            k.tr(ptm[:, kc * 128:(kc + 1) * 128], ocb_[:, kc * 128:(kc + 1) * 128], k.identb[:], [kocb], [kptm])
        yield
        k.cp('act', oT_[:, 0:nk, :], ptm[:, 0:nk * 128].rearrange("p (k t) -> p k t", k=nk), [kptm], [koT])
        yield
        for cg in range(2):
            for kc in range(KC):
                ok_ = (koT + 'b') if (ob_fm and kc >= 4) else koT
                k.mm(pm[cg][:], oT_[:, kc, :], Wout[:, kc, cg * 512:(cg + 1) * 512], kc == 0, kc == KC - 1,
                     [ok_, f'Wout{kc}'], [kpm[cg]])
        yield
        for j in range(2):
            k.act(junk[:, 0:512], pm[j][:], AF.Square, [kpm[j]], ['junk', kss], accum_out=ss_[:, j:j + 1])
        for j in range(2):
            k.cp('act', mix_[:, j * 512:(j + 1) * 512], pm[j][:], [kpm[j]], [kmix])
        k.dma('sp', ht_[:], hin[rows, :], w=[kht])
        yield
        k.tt('dve', ss_[:, 0:1], ss_[:, 0:1], ss_[:, 1:2], ALU.add, [kss], [kss])
        k.ts('dve', rs_[:], ss_[:, 0:1], 1.0 / D, EPS, ALU.mult, ALU.add, [kss], [krs])
        yield
        k.act(rs_[:], rs_[:], AF.Sqrt, [krs], [krs])
        yield
        k.recip(rs_[:], rs_[:], [krs], [krs])
        k.stt(tmp_[:], mix_[:], rs_[:], g1bc[:], ALU.mult, ALU.mult, [kmix, krs, 'g1bc'], [ktmp])
        yield
        k.tt('pool', ht_[:], ht_[:], tmp_[:], ALU.add, [kht, ktmp], [kht])
        k.dma('pool', hout[rows, :], ht_[:], r=[kht], final=True)

    pipeline(tile, NT)
    return k.finish()


def build_C3(NTOK, k=None):
    k = k or K()
    NB = NTOK // 512
    DFF = 4096
    FC = DFF // 128
    hin = k.din("hin", [NTOK, D])
    w1 = k.din("w1", [D, DFF])
    w2 = k.din("w2", [DFF, D])
    g4 = k.din("g4", [D])
    g5 = k.din("g5", [D])
    ident_d = k.din("ident", [128, 128])
    hout = k.dout("hout", [NTOK, D])
    k.consts(ident_d)
    g4c = k.gain_cols("g4c", g4)
    g5bc = k.bcast_row("g5bc", g5, D)
    W1 = k.load_weight("W1", w1, KC, DFF, gcol=g4c, gkey='g4c', stage_cols=512)
    W2 = k.load_weight("W2", w2, FC, D, stage_cols=512)
    ht = [k.sb(f"ht{i}", [128, D]) for i in range(4)]
    xn = [k.sb(f"xn{i}", [128, D], BF16) for i in range(2)]
    xT = k.sb("xT", [128, KC, 512], BF16)
    AT = k.sb("AT", [128, FC, 512], BF16)
    sq = [k.sb(f"sq{i}", [128, 512]) for i in range(2)]
    junk = k.sb("junk", [128, D], BF16)
    ss = [k.sb(f"ss{i}", [128, 1]) for i in range(2)]
    ss2 = [k.sb(f"ss2{i}", [128, 2]) for i in range(2)]
    rstd = [k.sb(f"rstd{i}", [128, 1]) for i in range(2)]
    rstd2 = [k.sb(f"rstdb{i}", [128, 1]) for i in range(2)]
    ss4 = k.sb("ss4", [128, 4])
    rs4 = k.sb("rs4", [128, 4])
    psT = k.ps("psT", [128, D], BF16)
    psU = [k.ps(f"psU{i}", [128, 512]) for i in range(3)]
    psD = [k.ps(f"psD{i}", [128, 512]) for i in range(4)]
    nu = 0
    for blk in range(NB):
        for tt in range(4):
            i = blk * 4 + tt
            k.dma('sp', ht[tt][:], hin[i * 128:(i + 1) * 128, :], w=[f'ht{tt}'])
        for tt in range(4):
            k.act(junk[:], ht[tt][:], AF.Square, [f'ht{tt}'], ['junk', f'nss{tt}'], accum_out=ss4[:, tt:tt + 1])
        k.ts('dve', rs4[:], ss4[:], 1.0 / D, EPS, ALU.mult, ALU.add, [f'nss{t_}' for t_ in range(4)], ['rs4'])
        k.act(rs4[:], rs4[:], AF.Sqrt, ['rs4'], ['rs4'])
        k.recip(rs4[:], rs4[:], ['rs4'], ['rs4'])
        for tt in range(4):
            b = tt % 2
            k.ts('dve', xn[b][:], ht[tt][:], rs4[:, tt:tt + 1], None, ALU.mult, None, [f'ht{tt}', 'rs4'], [f'xn{b}'])
            for kc in range(KC):
                k.tr(psT[:, kc * 128:(kc + 1) * 128], xn[b][:, kc * 128:(kc + 1) * 128], k.identb[:], [f'xn{b}'], ['psT'])
            k.cp('act', xT[:, :, tt * 128:(tt + 1) * 128], psT[:].rearrange("p (k t) -> p k t", k=KC), ['psT'], ['xT'])
        for fc in range(FC):
            pu = nu % 3
            nu += 1
            for kc in range(KC):
                k.mm(psU[pu][:], W1[:, kc, fc * 128:(fc + 1) * 128], xT[:, kc, :], kc == 0, kc == KC - 1,
                     [f'W1{kc}', 'xT'], [f'psU{pu}'])
            sb_ = fc % 2
            k.act(sq[sb_][:], psU[pu][:], AF.Square, [f'psU{pu}'], [f'sq{sb_}'])
            k.stt(AT[:, fc, :], psU[pu][:], 0.0, sq[sb_][:], ALU.is_gt, ALU.mult, [f'psU{pu}', f'sq{sb_}'], ['AT'])
        for tt in range(4):
            i = blk * 4 + tt
            b = i % 2
            rows = slice(i * 128, (i + 1) * 128)
            for cg in range(2):
                pd = 2 * b + cg
                for fc in range(FC):
                    k.mm(psD[pd][:], AT[:, fc, tt * 128:(tt + 1) * 128], W2[:, fc, cg * 512:(cg + 1) * 512],
                         fc == 0, fc == FC - 1, ['AT', f'W2{fc}'], [f'psD{pd}'])
            post_norm_res(k, [psD[2 * b][:], psD[2 * b + 1][:]], [f'psD{2 * b}', f'psD{2 * b + 1}'], ht[tt], f'ht{tt}',
                          g5bc, 'g5bc', [sq[0][:], sq[1][:]], ['sq0', 'sq1'], ss2[b], rstd2[b][:], junk, f'pn{b}')
            k.dma('pool', hout[rows, :], ht[tt][:], r=[f'ht{tt}'], final=True)
    return k.finish()


def build_C2(NTOK, k=None):
    k = k or K()
    NB = NTOK // 512
    MEM = 256
    hin = k.din("hin", [NTOK, D])
    mem = k.din("mem", [MEM, D])
    wq = k.din("wq", [D, D])
    wk = k.din("wk", [D, D])
    wv = k.din("wv", [D, D])
    wo = k.din("wo", [D, D])
    g2 = k.din("g2", [D])
    g3 = k.din("g3", [D])
    g6 = k.din("g6", [D])
    ident_d = k.din("ident", [128, 128])
    hout = k.dout("hout", [NTOK, D])
    k.consts(ident_d)
    g2c = k.gain_cols("g2c", g2)
    g6c = k.gain_cols("g6c", g6)
    g3bc = k.bcast_row("g3bc", g3, D)
    Wk = k.load_weight("Wk", wk, KC, D, gcol=g6c, gkey='g6c', stage_cols=1024)
    Wv = k.load_weight("Wv", wv, KC, D, gcol=g6c, gkey='g6c', stage_cols=1024)
    Wq = k.load_weight("Wq", wq, KC, D, gcol=g2c, gkey='g2c', stage_cols=1024)
    Wo = k.load_weight("Wo", wo, KC, D, stage_cols=1024)
    ht = [k.sb(f"ht{i}", [128, D]) for i in range(2)]
    xn = [k.sb(f"xn{i}", [128, D], BF16) for i in range(2)]
    xT = [k.sb(f"xT{i}", [128, KC, 512], BF16) for i in range(2)]
    memT = k.sb("memT", [128, KC, MEM], BF16)
    KT = k.sb("KT", [128, KC, MEM], BF16)
    V = k.sb("V", [128, 2, D], BF16)
    QT = [k.sb(f"QT{i}", [128, KC, 512], BF16) for i in range(2)]
    Pm = [k.sb(f"Pm{i}", [128, 4, MEM], BF16) for i in range(3)]
    Pn = [k.sb(f"Pn{i}", [128, 4, MEM], BF16) for i in range(3)]
    PT = [k.sb(f"PT{i}", [128, 8, 128], BF16) for i in range(3)]
    OT = [k.sb(f"OT{i}", [128, KC, 128], BF16) for i in range(3)]
    tmp = [k.sb(f"tmp{i}", [128, 512]) for i in range(2)]
    junk = k.sb("junk", [128, D], BF16)
    ss = [k.sb(f"ss{i}", [128, 1]) for i in range(2)]
    ss2 = [k.sb(f"ss2{i}", [128, 2]) for i in range(2)]
    rstd = [k.sb(f"rstd{i}", [128, 1]) for i in range(2)]
    rstd2 = [k.sb(f"rstdb{i}", [128, 1]) for i in range(2)]
    mx = [k.sb(f"mx{i}", [128, 4]) for i in range(3)]
    sm = [k.sb(f"sm{i}", [128, 4]) for i in range(3)]
    psT = k.ps("psT", [128, D], BF16)
    psA = k.ps("psA", [128, 1024])
    psS = k.ps("psS", [128, 1024])
    psX = k.ps("psX", [128, 1024])
    for mt in range(2):
        k.dma('sp', ht[mt][:], mem[mt * 128:(mt + 1) * 128, :], w=[f'ht{mt}'])
        norm_T(k, ht[mt][:], f'ht{mt}', xn[mt][:], f'xn{mt}', memT[:, :, mt * 128:(mt + 1) * 128], 'memT', psT[:], 'psT',
               ss[mt][:], rstd[mt][:], junk[:], f'n{mt}')
    for cc in range(KC):
        pa = cc % 2
        for kc in range(KC):
            k.mm(psA[:, pa * 512:pa * 512 + MEM], Wk[:, kc, cc * 128:(cc + 1) * 128], memT[:, kc, :], kc == 0, kc == KC - 1,
                 [f'Wk{kc}', 'memT'], [f'psA{pa}'])
        k.cp('act' if cc % 2 else 'dve', KT[:, cc, :], psA[:, pa * 512:pa * 512 + MEM], [f'psA{pa}'], [f'KT{cc}'])
    for mt in range(2):
        for cg in range(2):
            for kc in range(KC):
                k.mm(psX[:, cg * 512:(cg + 1) * 512], memT[:, kc, mt * 128:(mt + 1) * 128], Wv[:, kc, cg * 512:(cg + 1) * 512],
                     kc == 0, kc == KC - 1, ['memT', f'Wv{kc}'], [f'psX{cg}'])
            k.cp('act' if cg else 'dve', V[:, mt, cg * 512:(cg + 1) * 512], psX[:, cg * 512:(cg + 1) * 512], [f'psX{cg}'], [f'V{mt}{cg}'])
    xt6 = [k.sb(f"xt6_{i}", [128, D]) for i in range(6)]
    ss6 = [k.sb(f"ss6_{i}", [128, 1]) for i in range(4)]
    rs6 = [k.sb(f"rs6_{i}", [128, 1]) for i in range(5)]
    xn3 = [k.sb(f"xn3_{i}", [128, D], BF16) for i in range(3)]
    psTx = k.ps("psTx", [128, D], BF16)

    def tile(i):
        blk, tt = divmod(i, 4)
        xb = blk % 2
        b = i % 3
        rows = slice(i * 128, (i + 1) * 128)
        tsl = slice(tt * 128, (tt + 1) * 128)
        def T(lst, nm):
            j = i % len(lst)
            return lst[j], f'{nm}{j}'
        xt_, kxt = T(xt6, 'xt6'); ss_, kss = T(ss6, 'ss6'); rs_, krs = T(rs6, 'rs6'); xn_, kxn = T(xn3, 'xn3')
        hb = i % 2
        k.dma('sp', xt_[:], hin[rows, :], w=[kxt])
        yield
        k.act(junk[:], xt_[:], AF.Square, [kxt], ['junk', kss], accum_out=ss_[:])
        yield
        k.ts('dve', rs_[:], ss_[:], 1.0 / D, EPS, ALU.mult, ALU.add, [kss], [krs])
        yield
        k.act(rs_[:], rs_[:], AF.Sqrt, [krs], [krs])
        yield
        k.recip(rs_[:], rs_[:], [krs], [krs])
        k.ts('dve', xn_[:], xt_[:], rs_[:], None, ALU.mult, None, [kxt, krs], [kxn])
        yield
        for kc in range(KC):
            k.tr(psTx[:, kc * 128:(kc + 1) * 128], xn_[:, kc * 128:(kc + 1) * 128], k.identb[:], [kxn], ['psTx'])
        yield
        k.cp('act', xT[xb][:, :, tsl], psTx[:].rearrange("p (k t) -> p k t", k=KC), ['psTx'], [f'xT{xb}'])
        yield
        if tt == 3:
            for cc in range(KC):
                pa = cc % 2
                for kc in range(KC):
                    k.mm(psA[:, pa * 512:(pa + 1) * 512], Wq[:, kc, cc * 128:(cc + 1) * 128], xT[xb][:, kc, :], kc == 0, kc == KC - 1,
                         [f'Wq{kc}', f'xT{xb}'], [f'psA{pa}'])
                k.cp('act' if cc % 2 else 'dve', QT[xb][:, cc, :], psA[:, pa * 512:(pa + 1) * 512], [f'psA{pa}'], [f'QT{xb}{cc}'])
        yield
        yield
        yield
        yield
        for h in range(4):
            sb_ = h // 2
            for j in range(2):
                cc = 2 * h + j
                k.mm(psS[:, h * MEM:(h + 1) * MEM], QT[xb][:, cc, tsl], KT[:, cc, :], j == 0, j == 1,
                     [f'QT{xb}{cc}', f'KT{cc}'], [f'psS{sb_}'])
        k.P.op('dve', lambda e, b=b: e.tensor_reduce(out=mx[b][:], in_=psS[:].rearrange("p (h m) -> p h m", h=4),
                                                    axis=AX.X, op=ALU.max),
               reads=['psS0', 'psS1'], writes=[f'mx{b}'])
        k.ts('dve', mx[b][:], mx[b][:], -1.0 / 16.0, None, ALU.mult, None, [f'mx{b}'], [f'mx{b}'])
        for h in range(4):
            k.act(Pm[b][:, h, :], psS[:, h * MEM:(h + 1) * MEM], AF.Exp, [f'psS{h // 2}', f'mx{b}'], [f'Pm{b}', f'sm{b}'],
                  scale=1.0 / 16.0, bias=mx[b][:, h:h + 1], accum_out=sm[b][:, h:h + 1])
        k.recip(sm[b][:], sm[b][:], [f'sm{b}'], [f'sm{b}'])
        k.tt('dve', Pn[b][:], Pm[b][:], sm[b][:].unsqueeze(2).broadcast_to([128, 4, MEM]), ALU.mult,
             [f'Pm{b}', f'sm{b}'], [f'Pn{b}'])
        yield
        for h in range(4):
            for mt in range(2):
                k.tr(psT[:, (h * 2 + mt) * 128:(h * 2 + mt + 1) * 128], Pn[b][:, h, mt * 128:(mt + 1) * 128], k.identb[:],
                     [f'Pn{b}'], ['psT'])
        k.cp('act', PT[b][:], psT[:].rearrange("p (k t) -> p k t", k=8), ['psT'], [f'PT{b}'])
        for cc in range(KC):
            h = cc // 2
            pa = cc // 4
            for mt in range(2):
                k.mm(psA[:, cc * 128:(cc + 1) * 128], V[:, mt, cc * 128:(cc + 1) * 128], PT[b][:, h * 2 + mt, :],
                     mt == 0, mt == 1, [f'V{mt}{cc // 4}', f'PT{b}'], [f'psA{pa}'])
        k.cp('dve', OT[b][:, 0:4, :], psA[:, 0:512].rearrange("p (k t) -> p k t", k=4), ['psA0'], [f'OT{b}_0'])
        k.cp('act', OT[b][:, 4:8, :], psA[:, 512:1024].rearrange("p (k t) -> p k t", k=4), ['psA1'], [f'OT{b}_1'])
        k.dma('sp', ht[hb][:], hin[rows, :], w=[f'ht{hb}'])
        yield
        for cg in range(2):
            for cc in range(KC):
                k.mm(psX[:, cg * 512:(cg + 1) * 512], OT[b][:, cc, :], Wo[:, cc, cg * 512:(cg + 1) * 512],
                     cc == 0, cc == KC - 1, [f'OT{b}_{cc // 4}', f'Wo{cc}'], [f'psX{cg}'])
        post_norm_res(k, [psX[:, 0:512], psX[:, 512:1024]], ['psX0', 'psX1'], ht[hb], f'ht{hb}',
                      g3bc, 'g3bc', [tmp[0][:], tmp[1][:]], ['tmp0', 'tmp1'], ss2[b % 2], rstd2[b % 2][:], junk, f'pn{b % 2}')
        k.dma('pool', hout[rows, :], ht[hb][:], r=[f'ht{hb}'], final=True)

    pipeline(tile, NTOK // 128)
    return k.finish()


def build_A2(NTOK, NC, fm, NF, k=None):
    k = k or K()
    NB = NTOK // 512
    x = k.din("x", [NTOK, D])
    gain = k.din("gain", [D])
    W = k.din("W", [D, NC])
    ident_d = k.din("ident", [128, 128])
    out = k.dout("out", [NTOK, NC])
    outT = k.dout("outT", [NF, NTOK])
    k.consts(ident_d)
    gc = k.gain_cols("gc", gain)
    Wb = k.load_weight("Wb", W, KC, NC, gcol=gc, gkey='gc', stage_cols=1408)
    cgs = [(c0, min(512, NC - c0)) for c0 in range(0, NC, 512)]
    def ring(nm, shape, n, dt=F32):
        return [k.sb(f"{nm}{j}", shape, dt) for j in range(n)]
    xt = ring("xt", [128, D], 6)
    xn = ring("xn", [128, D], 3, BF16)
    xT = [k.sb(f"xT{i}", [128, KC, 512], BF16) for i in range(2)]
    ot = [k.sb(f"ot{i}", [128, NC]) for i in range(2)]
    ft = [k.sb(f"ft{i}", [128, 512]) for i in range(2)]
    junk = k.sb("junk", [128, D], BF16)
    ss = ring("ss", [128, 1], 4)
    rstd = ring("rstd", [128, 1], 5)
    psT = k.ps("psT", [128, D], BF16)
    psO = [k.ps(f"psO{i}", [128, 512]) for i in range(4)]
    psF = [k.ps(f"psF{i}", [128, 512]) for i in range(2)]
    cnt = {'no': 0, 'nf': 0}

    def tile(i):
        blk, tt = divmod(i, 4)
        xb = blk % 2
        def T(lst, nm):
            j = i % len(lst)
            return lst[j], f'{nm}{j}'
        xt_, kxt = T(xt, 'xt'); xn_, kxn = T(xn, 'xn'); ss_, kss = T(ss, 'ss'); rs_, krs = T(rstd, 'rstd')
        k.dma('sp', xt_[:], x[i * 128:(i + 1) * 128, :], w=[kxt])
        yield
        k.act(junk[:], xt_[:], AF.Square, [kxt], ['junk', kss], accum_out=ss_[:])
        yield
        k.ts('dve', rs_[:], ss_[:], 1.0 / D, EPS, ALU.mult, ALU.add, [kss], [krs])
        yield
        k.act(rs_[:], rs_[:], AF.Sqrt, [krs], [krs])
        yield
        k.recip(rs_[:], rs_[:], [krs], [krs])
        k.ts('dve', xn_[:], xt_[:], rs_[:], None, ALU.mult, None, [kxt, krs], [kxn])
        yield
        for kc in range(KC):
            k.tr(psT[:, kc * 128:(kc + 1) * 128], xn_[:, kc * 128:(kc + 1) * 128], k.identb[:], [kxn], ['psT'])
        yield
        k.cp('act', xT[xb][:, :, tt * 128:(tt + 1) * 128], psT[:].rearrange("p (k t) -> p k t", k=KC), ['psT'], [f'xT{xb}'])
        yield
        if tt != 3:
            return
        for t2 in range(4):
            i2 = blk * 4 + t2
            b = i2 % 2
            for ci, (c0, cw) in enumerate(cgs):
                pb = cnt['no'] % 4
                cnt['no'] += 1
                for kc in range(KC):
                    k.mm(psO[pb][:, 0:cw], xT[xb][:, kc, t2 * 128:(t2 + 1) * 128], Wb[:, kc, c0:c0 + cw], kc == 0, kc == KC - 1,
                         [f'xT{xb}', f'Wb{kc}'], [f'psO{pb}'])
                k.cp('dve' if pb % 2 == 0 else 'act', ot[b][:, c0:c0 + cw], psO[pb][:, 0:cw], [f'psO{pb}'], [f'ot{b}_{pb % 2}'])
            k.dma('pool', out[i2 * 128:(i2 + 1) * 128, :], ot[b][:], r=[f'ot{b}_0', f'ot{b}_1'], final=True)
        for (c0, cw, r0) in fm:
            pf = cnt['nf'] % 2
            cnt['nf'] += 1
            for kc in range(KC):
                k.mm(psF[pf][0:cw, :], Wb[:, kc, c0:c0 + cw], xT[xb][:, kc, :], kc == 0, kc == KC - 1,
                     [f'Wb{kc}', f'xT{xb}'], [f'psF{pf}'])
            k.cp('dve' if pf == 0 else 'act', ft[pf][0:cw, :], psF[pf][0:cw, :], [f'psF{pf}'], [f'ft{pf}'])
            k.dma('pool', outT[r0:r0 + cw, blk * 512:(blk + 1) * 512], ft[pf][0:cw, :], r=[f'ft{pf}'], final=True)

    pipeline(tile, NTOK // 128)
    return k.finish()


def gen_GLA(L, k):
    NT = L // 128
    qT = k.din("qT", [128, L])
    kT = k.din("kT", [128, L])
    ktok = k.din("ktok", [L, 128])
    v = k.din("v", [L, 256])
    gate = k.din("gate", [L, 256])
    dlrT = k.din("dlrT", [16, L])
    w2 = k.din("w2", [16, 128])
    bdec = k.din("bdec", [1, 128])
    gn = k.din("gn", [256])
    triu_d = k.din("triu", [128, 128])
    trigt_d = k.din("trigt", [128, 128])
    oa = k.dout("oa", [L, 256])

    triu = k.sb("triu_s", [128, 128])
    trigt = k.sb("trigt_s", [128, 128])
    k.dma('sp', triu[:], triu_d, w=['triu'])
    k.dma('sp', trigt[:], trigt_d, w=['trigt'])
    w2s = k.sb("w2s", [16, 128])
    k.dma('sp', w2s[:], w2, w=['w2s'])
    bds = k.sb("bds", [1, 128])
    k.dma('sp', bds[:], bdec, w=['bds'])
    ones1 = k.sb("ones1", [1, 128])
    k.memset('dve', ones1[:], 1.0, ['ones1'])
    gnbc = k.bcast_row("gnbc", gn, 256)
    S = k.sb("S", [128, 128], mybir.dt.float32r)
    zS = k.sb("zS", [128, 128])
    k.memset('dve', zS[:], 0.0, ['zS'])
    k.cp('dve', S[:], zS[:], ['zS'], ['S'])
    rm = k.sb("rm", [128, 2])
    k.memset('dve', rm[:], 0.0, ['rm'])
    k.memset('dve', rm[0:64, 0:1], 0.125, ['rm'])
    k.memset('dve', rm[64:128, 1:2], 0.125, ['rm'])

    def ring(nm, shape, n, dt=F32):
        return [k.sb(f"{nm}{j}", shape, dt) for j in range(n)]
    FR_ = mybir.dt.float32r
    triur = k.sb("triur", [128, 128], FR_)
    trigtr = k.sb("trigtr", [128, 128], FR_)
    k.cp('dve', triur[:], triu[:], ['triu'], ['triur'])
    k.cp('dve', trigtr[:], trigt[:], ['trigt'], ['trigtr'])
    vr = ring("vr", [128, 256], 10, FR_)
    qTt, kTt, kt, gt = ring("qTt", [128, 128], 8), ring("kTt", [128, 128], 8), ring("kt", [128, 128], 8), ring("gt", [128, 256], 8)
    vt = ring("vt", [128, 256], 11)
    dt_ = ring("dt", [16, 128], 3)
    la = ring("la", [128, 128], 4, mybir.dt.float32r)
    sg = ring("sg", [128, 256], 16)
    EqT, EkT, Eks = ring("EqT", [128, 128], 7), ring("EkT", [128, 128], 3), ring("Eks", [128, 128], 3)
    qin, kin, kst = ring("qin", [128, 2, 128], 5, mybir.dt.float32r), ring("kin", [128, 128], 3, mybir.dt.float32r), ring("kst", [128, 128], 5, mybir.dt.float32r)
    sc0, sc1 = ring("sc0_", [128, 128], 3, mybir.dt.float32r), ring("sc1_", [128, 128], 3, mybir.dt.float32r)
    osr = ring("osr", [128, 256], 6)
    osb = ring("osb", [128, 256], 3)
    ss, rs = ring("ss", [128, 2], 4), ring("rs", [128, 2], 5)
    ot = ring("ot", [128, 256], 3)
    junk = k.sb("junk", [128, 128])
    psZ = [k.ps(f"psZ{j}", [128, 512]) for j in range(2)]
    psA = [k.ps(f"psA{j}", [128, 512]) for j in range(2)]
    psB = [k.ps(f"psB{j}", [128, 512]) for j in range(2)]
    psC = [k.ps(f"psC{j}", [128, 512]) for j in range(2)]

    def tile(i):
        rows = slice(i * 128, (i + 1) * 128)
        R = lambda lst: (lst[i % len(lst)], f'{lst[0].name if hasattr(lst[0], "name") else id(lst)}_{i % len(lst)}')
        def T(lst, nm):
            j = i % len(lst)
            return lst[j], f'{nm}{j}'
        q_, kq = T(qTt, 'qTt'); kT_, kkT = T(kTt, 'kTt'); kt_, kkt = T(kt, 'kt'); v_, kv = T(vt, 'vt'); g_, kg = T(gt, 'gt')
        d_, kd = T(dt_, 'dt'); la_, kla = T(la, 'la'); sg_, ksg = T(sg, 'sg')
        Eq, kEq = T(EqT, 'EqT'); Ek, kEk = T(EkT, 'EkT'); Es, kEs = T(Eks, 'Eks')
        qi, kqi = T(qin, 'qin'); ki, kki = T(kin, 'kin'); ks, kks = T(kst, 'kst')
        scs = [T(sc0, 'sc0_'), T(sc1, 'sc1_')]
        orw, korw = T(osr, 'osr'); ob_, kob = T(osb, 'osb'); ss_, kss = T(ss, 'ss'); rs_, krs = T(rs, 'rs'); ot_, kot = T(ot, 'ot')
        pz, kpz = psZ[i % 2], f'psZ{i % 2}'
        pa, kpa = psA[i % 2], f'psA{i % 2}'
        pb, kpb = psB[i % 2], f'psB{i % 2}'
        pc, kpc = psC[i % 2], f'psC{i % 2}'
        k.dma('sp', q_[:], qT[:, rows], w=[kq])
        k.dma('sp', kT_[:], kT[:, rows], w=[kkT])
        k.dma('sp', kt_[:], ktok[rows, :], w=[kkt])
        k.dma('sp', v_[:], v[rows, :], w=[kv])
        k.dma('sp', g_[:], gate[rows, :], w=[kg])
        k.dma('sp', d_[:], dlrT[:, rows], w=[kd])
        yield
        k.mm(pz[:, 0:128], d_[:], w2s[:], True, False, [kd, 'w2s'], [kpz])
        k.mm(pz[:, 0:128], ones1[:], bds[:], False, True, ['ones1', 'bds'], [kpz])
        yield
        k.act(la_[:], pz[:, 0:128], AF.Exp, [kpz], [kla], scale=-1.0)
        k.act(la_[:], la_[:].bitcast(F32), AF.Ln, [kla], [kla], bias=1.0)
        k.act(sg_[:], g_[:], AF.Exp, [kg], [ksg], scale=-1.0)
        vr_, kvr = T(vr, 'vr')
        k.cp('act', vr_[:], v_[:], [kv], [kvr])
        yield
        k.ts('dve', la_[:], la_[:].bitcast(F32), -1.0 / 16.0, None, ALU.mult, None, [kla], [kla])
        k.ts('dve', sg_[:], sg_[:], 1.0, None, ALU.add, None, [ksg], [ksg])
        k.recip(sg_[:], sg_[:], [ksg], [ksg])
        yield
        k.mm(pa[:, 0:128], la_[:], triur[:], True, True, [kla, 'triur'], [kpa])
        k.mm(pa[:, 128:256], trigtr[:], la_[:], True, True, [kla, 'trigtr'], [kpa])
        yield
        k.act(Eq[:], pa[:, 0:128], AF.Exp, [kpa], [kEq])
        k.act(Ek[:], pa[:, 0:128], AF.Exp, [kpa], [kEk], scale=-1.0)
        k.act(Es[:], pa[:, 128:256], AF.Exp, [kpa], [kEs])
        yield
        for h in range(2):
            k.stt(qi[:, h, :], q_[:], rm[:, h:h + 1], Eq[:], ALU.mult, ALU.mult, [kq, kEq, 'rm'], [kqi])
        k.tt('pool', ki[:], kT_[:], Ek[:], ALU.mult, [kkT, kEk], [kki])
        k.tt('pool', ks[:], kt_[:], Es[:], ALU.mult, [kkt, kEs], [kks])
        k.tt('pool', sg_[:], sg_[:], g_[:], ALU.mult, [ksg, kg], [ksg])
        yield
        for h in range(2):
            hp = slice(h * 64, (h + 1) * 64)
            k.mm(pb[:, h * 128:(h + 1) * 128], ki[:], qi[:, h, :], True, True, [kki, kqi], [kpb])
        yield
        for h in range(2):
            k.tt('dve', scs[h][0][:], pb[:, h * 128:(h + 1) * 128], triu[:], ALU.mult, [kpb, 'triu'], [scs[h][1]])
        yield
        for h in range(2):
            hp = slice(h * 64, (h + 1) * 64)
            k.mm(pc[:, h * 128:(h + 1) * 128], scs[h][0][:], vr_[:, h * 128:(h + 1) * 128], True, False, [scs[h][1], kvr], [kpc])
            k.mm(pc[:, h * 128:(h + 1) * 128], qi[:, h, :], S[:], False, True, [kqi, 'S'], [kpc])
        k.mm(pc[:, 256:512], ks[:], vr_[:], True, True, [kks, kvr], [kpc])
        yield
        for h in range(2):
            hp = slice(h * 64, (h + 1) * 64)
            k.stt(S[hp, :], S[hp, :].bitcast(F32), Eq[hp, 127:128], pc[hp, 256 + h * 128:256 + (h + 1) * 128], ALU.mult, ALU.add,
                  ['S', kEq, kpc], ['S'])
        k.cp('act', orw[:], pc[:, 0:256], [kpc], [korw])
        yield
        for h in range(2):
            k.act(junk[:], orw[:, h * 128:(h + 1) * 128], AF.Square, [korw], ['junk', kss], accum_out=ss_[:, h:h + 1])
        yield
        k.ts('dve', rs_[:], ss_[:], 1.0 / 128.0, EPS, ALU.mult, ALU.add, [kss], [krs])
        yield
        k.act(rs_[:], rs_[:], AF.Ln, [krs], [krs])
        k.act(rs_[:], rs_[:], AF.Exp, [krs], [krs], scale=-0.5)
        yield
        for h in range(2):
            hs = slice(h * 128, (h + 1) * 128)
            k.stt(ob_[:, hs], orw[:, hs], rs_[:, h:h + 1], gnbc[:, hs], ALU.mult, ALU.mult, [korw, krs, 'gnbc'], [kob])
        yield
        k.tt('pool', ot_[:], ob_[:], sg_[:], ALU.mult, [kob, ksg], [kot])
        k.dma('pool', oa[rows, :], ot_[:], r=[kot], final=True)

    yield from pipeline_gen(tile, NT)


def build_GLA(L, k=None):
    k = k or K()
    for _ in gen_GLA(L, k):
        pass
    return k.finish()


TWO_PI = 2.0 * math.pi
C1 = 6.28125
C2 = TWO_PI - 6.28125
PI_LO = 3.1415925


def range_sincos(k, x, xkey, shape, s_out, c_out, skey, ckey, pfx):
    if not hasattr(k, 'rr_cache'):
        k.rr_cache = {}
    if pfx not in k.rr_cache:
        k.rr_cache[pfx] = (k.sb(pfx + "kf", shape), k.sb(pfx + "ki", shape, I32), k.sb(pfx + "r", shape), k.sb(pfx + "m", shape))
    kf, ki, r, m = k.rr_cache[pfx]
    a = lambda t: t[:]
    K1, K2, K3, K4 = pfx + 'kf', pfx + 'ki', pfx + 'r', pfx + 'm'
    k.ts('dve', a(kf), x, 1.0 / TWO_PI, None, ALU.mult, None, [xkey], [K1])
    k.cp('dve', a(ki), a(kf), [K1], [K2])
    k.cp('dve', a(kf), a(ki), [K2], [K1])
    k.stt(a(r), a(kf), -C1, x, ALU.mult, ALU.add, [K1, xkey], [K3])
    k.stt(a(r), a(kf), -C2, a(r), ALU.mult, ALU.add, [K1, K3], [K3])
    k.ts('dve', a(m), a(r), math.pi, -TWO_PI, ALU.is_gt, ALU.mult, [K3], [K4])
    k.tt('dve', a(r), a(r), a(m), ALU.add, [K3, K4], [K3])
    k.ts('dve', a(m), a(r), -math.pi, TWO_PI, ALU.is_lt, ALU.mult, [K3], [K4])
    k.tt('dve', a(r), a(r), a(m), ALU.add, [K3, K4], [K3])
    k.ts('dve', a(kf), a(r), PI_LO, -PI_LO, ALU.min, ALU.max, [K3], [K1])
    k.act(s_out, a(kf), AF.Sin, [K1], [skey])
    k.ts('dve', a(r), a(r), math.pi / 2, None, ALU.add, None, [K3], [K3])
    k.ts('dve', a(m), a(r), math.pi, -TWO_PI, ALU.is_gt, ALU.mult, [K3], [K4])
    k.tt('dve', a(r), a(r), a(m), ALU.add, [K3, K4], [K3])
    k.ts('dve', a(kf), a(r), PI_LO, -PI_LO, ALU.min, ALU.max, [K3], [K1])
    k.act(c_out, a(kf), AF.Sin, [K1], [ckey])


def gen_S5(L, k):
    NT = L // 128
    NS = 1024
    uT = k.din("uT", [256, L])
    u = k.din("u", [L, 256])
    lam_re = k.din("lam_re", [NS])
    lam_im = k.din("lam_im", [NS])
    lstep = k.din("lstep", [NS])
    Bre = k.din("Bre", [2, 128, 512])
    Bim = k.din("Bim", [2, 128, 512])
    Cre = k.din("Cre", [8, 128, 32])
    Cim = k.din("Cim", [8, 128, 32])
    dsk = k.din("dsk", [256])
    triu_d = k.din("triu", [128, 128])
    iop_d = k.din("iota_p", [128, 1])
    iof_d = k.din("iota_f", [128, 128])
    y = k.dout("y", [L, 256])

    k.push_scope([("triu_s", [128, 128], F32), ("dbc", [128, 256], F32), ("BBr", [128, 2, 512], mybir.dt.float32r), ("BBi", [128, 2, 512], mybir.dt.float32r),
                  ("Pr", [128, NS], F32), ("Pi", [128, NS], F32), ("Qr", [128, 8, 128], F32), ("Qi", [128, 8, 128], F32),
                  ("L128r", [128, 8], F32), ("L128i", [128, 8], F32), ("Cr", [128, 8, 32], F32), ("nCi", [128, 8, 32], F32),
                  ("car_r", [128, 8], F32), ("car_i", [128, 8], F32), ("ntriu", [128, 128], mybir.dt.float32r), ("nCr", [128, 8, 32], mybir.dt.float32r), ("triur", [128, 128], mybir.dt.float32r), ("Crr", [128, 8, 32], mybir.dt.float32r), ("nCir", [128, 8, 32], mybir.dt.float32r)])
    triu = k.sb("triu_s", [128, 128])
    k.dma('sp', triu[:], triu_d, w=['triu'])
    iop = k.sb("iop", [128, 1])
    k.dma('sp', iop[:], iop_d, w=['iop'])
    negp = k.sb("negp", [128, 1])
    k.ts('dve', negp[:], iop[:], -1.0, None, ALU.mult, None, ['iop'], ['negp'])
    iof = k.sb("iof", [128, 128])
    k.dma('sp', iof[:], iof_d, w=['iof'])
    dbc = k.bcast_row("dbc", dsk, 256)
    R = [128, NS]
    lr = k.bcast_row("lr", lam_re, NS)
    li = k.bcast_row("li", lam_im, NS)
    dl = k.bcast_row("dl", lstep, NS)
    k.ts('dve', lr[:], lr[:], -1e-4, None, ALU.min, None, ['lr'], ['lr'])
    k.act(dl[:], dl[:], AF.Exp, ['dl'], ['dl'])
    a_ = k.sb("a_", R)
    th = k.sb("th", R)
    k.tt('dve', a_[:], lr[:], dl[:], ALU.mult, ['lr', 'dl'], ['a_'])
    k.tt('dve', th[:], li[:], dl[:], ALU.mult, ['li', 'dl'], ['th'])
    sn = k.sb("sn", R)
    cs = k.sb("cs", R)
    range_sincos(k, th[:], 'th', R, sn[:], cs[:], 'sn', 'cs', 'rr_')
    ea = k.sb("ea", R)
    k.act(ea[:], a_[:], AF.Exp, ['a_'], ['ea'])
    nr = k.sb("nr", R)
    ni = k.sb("ni", R)
    k.tt('dve', nr[:], ea[:], cs[:], ALU.mult, ['ea', 'cs'], ['nr'])
    k.ts('dve', nr[:], nr[:], -1.0, None, ALU.add, None, ['nr'], ['nr'])
    k.tt('dve', ni[:], ea[:], sn[:], ALU.mult, ['ea', 'sn'], ['ni'])
    den = k.sb("den", R)
    t0 = k.sb("t0", R)
    k.tt('dve', den[:], lr[:], lr[:], ALU.mult, ['lr'], ['den'])
    k.tt('dve', t0[:], li[:], li[:], ALU.mult, ['li'], ['t0'])
    k.tt('dve', den[:], den[:], t0[:], ALU.add, ['den', 't0'], ['den'])
    k.recip(den[:], den[:], ['den'], ['den'])
    gr = k.sb("gr", R)
    gi = k.sb("gi", R)
    k.tt('dve', gr[:], nr[:], lr[:], ALU.mult, ['nr', 'lr'], ['gr'])
    k.tt('dve', t0[:], ni[:], li[:], ALU.mult, ['ni', 'li'], ['t0'])
    k.tt('dve', gr[:], gr[:], t0[:], ALU.add, ['gr', 't0'], ['gr'])
    k.tt('dve', gr[:], gr[:], den[:], ALU.mult, ['gr', 'den'], ['gr'])
    k.tt('dve', gi[:], ni[:], lr[:], ALU.mult, ['ni', 'lr'], ['gi'])
    k.tt('dve', t0[:], nr[:], li[:], ALU.mult, ['nr', 'li'], ['t0'])
    k.tt('dve', gi[:], gi[:], t0[:], ALU.subtract, ['gi', 't0'], ['gi'])
    k.tt('dve', gi[:], gi[:], den[:], ALU.mult, ['gi', 'den'], ['gi'])
    Br = k.sb("Br", [128, 2, 512])
    Bi = k.sb("Bi", [128, 2, 512])
    BBr = k.sb("BBr", [128, 2, 512])
    BBi = k.sb("BBi", [128, 2, 512])
    for hc in range(2):
        k.dma('sp', Br[:, hc, :], Bre[hc], w=[f'Br{hc}'])
        k.dma('sp', Bi[:, hc, :], Bim[hc], w=[f'Bi{hc}'])
    grv = gr[:].rearrange("p (h n) -> p h n", h=2)
    giv = gi[:].rearrange("p (h n) -> p h n", h=2)
    t0v = t0[:].rearrange("p (h n) -> p h n", h=2)
    BK = ['Br0', 'Br1', 'Bi0', 'Bi1']
    k.tt('dve', BBr[:], grv, Br[:], ALU.mult, ['gr'] + BK, ['BBr'])
    k.tt('dve', t0v, giv, Bi[:], ALU.mult, ['gi'] + BK, ['t0'])
    k.tt('dve', BBr[:], BBr[:].bitcast(F32), t0v, ALU.subtract, ['BBr', 't0'], ['BBr'])
    k.tt('dve', BBi[:], grv, Bi[:], ALU.mult, ['gr'] + BK, ['BBi'])
    k.tt('dve', t0v, giv, Br[:], ALU.mult, ['gi'] + BK, ['t0'])
    k.tt('dve', BBi[:], BBi[:].bitcast(F32), t0v, ALU.add, ['BBi', 't0'], ['BBi'])
    ang = k.sb("ang", R)
    k.ts('dve', ang[:], th[:], iop[:, 0:1], None, ALU.mult, None, ['th', 'iop'], ['ang'])
    Pr = k.sb("Pr", R)
    Pi = k.sb("Pi", R)
    range_sincos(k, ang[:], 'ang', R, sn[:], cs[:], 'sn', 'cs', 'rr_')
    k.act(ea[:], a_[:], AF.Exp, ['a_', 'negp'], ['ea'], scale=negp[:, 0:1])
    k.tt('dve', Pr[:], ea[:], cs[:], ALU.mult, ['ea', 'cs'], ['Pr'])
    k.stt(Pi[:], ea[:], -1.0, sn[:], ALU.mult, ALU.mult, ['ea', 'sn'], ['Pi'])
    Cs = [128, 8]
    lrc = k.sb("lrc", Cs)
    lic = k.sb("lic", Cs)
    dlc = k.sb("dlc", Cs)
    cv = lambda d: d.rearrange("(blk p) -> p blk", p=128)
    k.dma('sp', lrc[:], cv(lam_re), w=['lrc'], allow_slow_non_contiguous=True)
    k.dma('sp', lic[:], cv(lam_im), w=['lic'], allow_slow_non_contiguous=True)
    k.dma('sp', dlc[:], cv(lstep), w=['dlc'], allow_slow_non_contiguous=True)
    k.ts('dve', lrc[:], lrc[:], -1e-4, None, ALU.min, None, ['lrc'], ['lrc'])
    k.act(dlc[:], dlc[:], AF.Exp, ['dlc'], ['dlc'])
    ac = k.sb("ac", Cs)
    thc = k.sb("thc", Cs)
    k.tt('dve', ac[:], lrc[:], dlc[:], ALU.mult, ['lrc', 'dlc'], ['ac'])
    k.tt('dve', thc[:], lic[:], dlc[:], ALU.mult, ['lic', 'dlc'], ['thc'])
    Qr = k.sb("Qr", [128, 8, 128])
    Qi = k.sb("Qi", [128, 8, 128])
    angv = ang[:].rearrange("p (b t) -> p b t", b=8)
    eav = ea[:].rearrange("p (b t) -> p b t", b=8)
    for blk in range(8):
        k.ts('dve', angv[:, blk, :], iof[:], thc[:, blk:blk + 1], None, ALU.mult, None, ['iof', 'thc'], ['ang'])
    range_sincos(k, ang[:], 'ang', R, sn[:], cs[:], 'sn', 'cs', 'rr_')
    for blk in range(8):
        k.act(eav[:, blk, :], iof[:], AF.Exp, ['iof', 'ac'], ['ea'], scale=ac[:, blk:blk + 1])
    k.tt('dve', Qr[:].rearrange("p b t -> p (b t)"), ea[:], cs[:], ALU.mult, ['ea', 'cs'], ['Qr'])
    k.tt('dve', Qi[:].rearrange("p b t -> p (b t)"), ea[:], sn[:], ALU.mult, ['ea', 'sn'], ['Qi'])
    a128 = k.sb("a128", Cs)
    s128 = k.sb("s128", Cs)
    c128 = k.sb("c128", Cs)
    L128r = k.sb("L128r", Cs)
    L128i = k.sb("L128i", Cs)
    k.ts('dve', a128[:], thc[:], 128.0, None, ALU.mult, None, ['thc'], ['a128'])
    range_sincos(k, a128[:], 'a128', Cs, s128[:], c128[:], 's128', 'c128', 'rc_')
    k.act(a128[:], ac[:], AF.Exp, ['ac', 's128', 'c128'], ['a128'], scale=128.0)
    k.tt('dve', L128r[:], a128[:], c128[:], ALU.mult, ['a128', 'c128'], ['L128r'])
    k.tt('dve', L128i[:], a128[:], s128[:], ALU.mult, ['a128', 's128'], ['L128i'])
    Cr = k.sb("Cr", [128, 8, 32])
    nCi = k.sb("nCi", [128, 8, 32])
    k.dma('sp', Cr[:], Cre.rearrange("b p c -> p b c"), w=['Cr'])
    k.dma('sp', nCi[:], Cim.rearrange("b p c -> p b c"), w=['nCi'])
    k.ts('dve', nCi[:], nCi[:], -1.0, None, ALU.mult, None, ['nCi'], ['nCi'])
    car_r = k.sb("car_r", Cs)
    car_i = k.sb("car_i", Cs)
    k.memset('dve', car_r[:], 0.0, ['car_r0', 'car_r1'])
    k.memset('dve', car_i[:], 0.0, ['car_i0', 'car_i1'])
    ntriu = k.sb("ntriu", [128, 128])
    k.ts('dve', ntriu[:], triu[:], -1.0, None, ALU.mult, None, ['triu'], ['ntriu'])
    nCr = k.sb("nCr", [128, 8, 32])
    k.ts('dve', nCr[:], Cr[:], -1.0, None, ALU.mult, None, ['Cr'], ['nCr'])
    triur = k.sb("triur", [128, 128])
    k.cp('dve', triur[:], triu[:], ['triu'], ['triur'])
    Crr = k.sb("Crr", [128, 8, 32])
    k.cp('dve', Crr[:], Cr[:], ['Cr'], ['Crr'])
    nCir = k.sb("nCir", [128, 8, 32])
    k.cp('dve', nCir[:], nCi[:], ['nCi'], ['nCir'])
    k.pop_scope()
    if hasattr(k, 'rr_cache'):
        del k.rr_cache
    def ring(nm, shape, n, dt=F32):
        return [k.sb(f"{nm}{j}", shape, dt) for j in range(n)]
    FR_ = mybir.dt.float32r
    uTt = ring("uTt", [128, 128], 3)
    uTr = ring("uTr", [128, 128], 3, FR_)
    ut = ring("ut", [128, 128], 5)
    yo = ring("yo", [128, 128], 9)
    m1, m2, m3, m4 = ring("m1_", [128, 512], 3, FR_), ring("m2_", [128, 512], 3, FR_), ring("m3_", [128, 512], 3, FR_), ring("m4_", [128, 512], 3, FR_)
    Xtr, Xti = ring("Xtr", [128, 512], 3), ring("Xti", [128, 512], 3)
    Gr, Gi = ring("Gr", [128, 4, 128], 4), ring("Gi", [128, 4, 128], 4)
    n1, n2, n3, n4 = ring("n1_", [128, 512], 3, FR_), ring("n2_", [128, 512], 3, FR_), ring("n3_", [128, 512], 3, FR_), ring("n4_", [128, 512], 3, FR_)
    Hr, Hi = ring("Hr", [128, 4, 128], 3), ring("Hi", [128, 4, 128], 3)
    cc1 = [k.sb(f"cc1_{h}", [128, 4]) for h in range(2)]
    cc2 = [k.sb(f"cc2_{h}", [128, 4]) for h in range(2)]
    psXr = k.ps("psXr", [128, 512])
    psXi = k.ps("psXi", [128, 512])
    psGr = k.ps("psGr", [128, 512])
    psGi = k.ps("psGi", [128, 512])
    psY = k.ps("psY", [128, 512])
    fl = lambda t: t[:].rearrange("p b t -> p (b t)")

    def item(j):
        i, hc = divmod(j, 2)
        rows = slice(i * 128, (i + 1) * 128)
        cs_ = slice(hc * 512, (hc + 1) * 512)
        bs = slice(hc * 4, (hc + 1) * 4)
        def T(lst, nm):
            q = j % len(lst)
            return lst[q], f'{nm}{q}'
        uT_, kuT = T(uTt, 'uTt'); uR_, kuR = T(uTr, 'uTr'); ut_, kut = T(ut, 'ut'); yo_, kyo = T(yo, 'yo')
        m1_, km1 = T(m1, 'm1'); m2_, km2 = T(m2, 'm2'); m3_, km3 = T(m3, 'm3'); m4_, km4 = T(m4, 'm4')
        Xr_, kXr = T(Xtr, 'Xtr'); Xi_, kXi = T(Xti, 'Xti'); Gr_, kGr = T(Gr, 'Gr'); Gi_, kGi = T(Gi, 'Gi')
        n1_, kn1 = T(n1, 'n1'); n2_, kn2 = T(n2, 'n2'); n3_, kn3 = T(n3, 'n3'); n4_, kn4 = T(n4, 'n4')
        Hr_, kHr = T(Hr, 'Hr'); Hi_, kHi = T(Hi, 'Hi')
        k.dma('sp', uT_[:], uT[hc * 128:(hc + 1) * 128, rows], w=[kuT])
        k.dma('sp', ut_[:], u[rows, hc * 128:(hc + 1) * 128], w=[kut])
        yield
        k.cp('act', uR_[:], uT_[:], [kuT], [kuR])
        yield
        k.mm(psXr[:], uR_[:], BBr[:, hc, :], True, True, [kuR, 'BBr'], ['psXr'])
        k.mm(psXi[:], uR_[:], BBi[:, hc, :], True, True, [kuR, 'BBi'], ['psXi'])
        yield
        k.tt('dve', m1_[:], psXr[:], Pr[:, cs_], ALU.mult, ['psXr', 'Pr'], [km1])
        k.tt('dve', m3_[:], psXr[:], Pi[:, cs_], ALU.mult, ['psXr', 'Pi'], [km3])
        k.tt('dve', m2_[:], psXi[:], Pi[:, cs_], ALU.mult, ['psXi', 'Pi'], [km2])
        k.tt('dve', m4_[:], psXi[:], Pr[:, cs_], ALU.mult, ['psXi', 'Pr'], [km4])
        yield
        k.tt('pool', yo_[:], ut_[:], dbc[:, hc * 128:(hc + 1) * 128], ALU.mult, [kut, 'dbc'], [kyo])
        yield
        for nb in range(4):
            ns = slice(nb * 128, (nb + 1) * 128)
            k.mm(psGr[:, ns], m1_[:, ns], triur[:], True, False, [km1, 'triur'], ['psGr'])
            k.mm(psGr[:, ns], m2_[:, ns], ntriu[:], False, True, [km2, 'ntriu'], ['psGr'])
            k.mm(psGi[:, ns], m3_[:, ns], triur[:], True, False, [km3, 'triur'], ['psGi'])
            k.mm(psGi[:, ns], m4_[:, ns], triur[:], False, True, [km4, 'triur'], ['psGi'])
        yield
        for nb in range(4):
            ns = slice(nb * 128, (nb + 1) * 128)
            k.act(Gr_[:, nb, :], psGr[:, ns], AF.Identity, ['psGr', f'car_r{hc}'], [kGr], bias=car_r[:, hc * 4 + nb:hc * 4 + nb + 1])
            k.act(Gi_[:, nb, :], psGi[:, ns], AF.Identity, ['psGi', f'car_i{hc}'], [kGi], bias=car_i[:, hc * 4 + nb:hc * 4 + nb + 1])
        yield
        gr127 = Gr_[:, :, 127]
        gi127 = Gi_[:, :, 127]
        CK = [f'cc1{hc}', f'cc2{hc}']
        k.tt('dve', cc1[hc][:], L128r[:, bs], gr127, ALU.mult, ['L128r', kGr], [CK[0]])
        k.tt('dve', cc2[hc][:], L128i[:, bs], gi127, ALU.mult, ['L128i', kGi], [CK[1]])
        k.tt('dve', car_r[:, bs], cc1[hc][:], cc2[hc][:], ALU.subtract, CK, [f'car_r{hc}'])
        k.tt('dve', cc1[hc][:], L128r[:, bs], gi127, ALU.mult, ['L128r', kGi], [CK[0]])
        k.tt('dve', cc2[hc][:], L128i[:, bs], gr127, ALU.mult, ['L128i', kGr], [CK[1]])
        k.tt('dve', car_i[:, bs], cc1[hc][:], cc2[hc][:], ALU.add, CK, [f'car_i{hc}'])
        yield
        qr = Qr[:, bs, :].rearrange("p b t -> p (b t)")
        qi = Qi[:, bs, :].rearrange("p b t -> p (b t)")
        k.tt('dve', n1_[:], fl(Gr_), qr, ALU.mult, [kGr, 'Qr'], [kn1])
        k.tt('dve', n2_[:], fl(Gi_), qi, ALU.mult, [kGi, 'Qi'], [kn2])
        k.tt('dve', n3_[:], fl(Gi_), qr, ALU.mult, [kGi, 'Qr'], [kn3])
        k.tt('dve', n4_[:], fl(Gr_), qi, ALU.mult, [kGr, 'Qi'], [kn4])
        yield
        for nb in range(4):
            blk = hc * 4 + nb
            ns = slice(nb * 128, (nb + 1) * 128)
            yo_s = psY[:, blk * 32:(blk + 1) * 32]
            k.mm(yo_s, n1_[:, ns], Crr[:, blk, :], True, False, [kn1, 'Crr'], ['psY'])
            k.mm(yo_s, n2_[:, ns], nCr[:, blk, :], False, False, [kn2, 'nCr'], ['psY'])
            k.mm(yo_s, n3_[:, ns], nCir[:, blk, :], False, False, [kn3, 'nCir'], ['psY'])
            k.mm(yo_s, n4_[:, ns], nCir[:, blk, :], False, True, [kn4, 'nCir'], ['psY'])
        yield
        k.tt('dve', yo_[:], yo_[:], psY[:, hc * 128:(hc + 1) * 128], ALU.add, [kyo, 'psY'], [kyo])
        yield
        k.dma('pool', y[rows, hc * 128:(hc + 1) * 128], yo_[:], r=[kyo], final=True)

    yield from pipeline_gen(item, 2 * NT)


def build_S5(L, k=None):
    k = k or K()
    for _ in gen_S5(L, k):
        pass
    return k.finish()


def s5_host_inputs(s, proj_u, prm):
    gs = slice(16 * s, 16 * s + 16)
    cs = slice(256 * s, 256 * s + 256)
    uc = np.ascontiguousarray(proj_u[:, cs])
    Bre = np.zeros((2, 128, 512), np.float32)
    Bim = np.zeros((2, 128, 512), np.float32)
    Cre = np.zeros((8, 128, 32), np.float32)
    Cim = np.zeros((8, 128, 32), np.float32)
    b_re, b_im = prm['s5_b_re'][gs], prm['s5_b_im'][gs]
    c_re, c_im = prm['s5_c_re'][gs], prm['s5_c_im'][gs]
    for g in range(16):
        hc, gl = g // 8, g % 8
        Bre[hc, gl * 16:(gl + 1) * 16, gl * 64:(gl + 1) * 64] = b_re[g].T
        Bim[hc, gl * 16:(gl + 1) * 16, gl * 64:(gl + 1) * 64] = b_im[g].T
        blk, g2 = g // 2, g % 2
        Cre[blk, g2 * 64:(g2 + 1) * 64, g2 * 16:(g2 + 1) * 16] = c_re[g].T
        Cim[blk, g2 * 64:(g2 + 1) * 64, g2 * 16:(g2 + 1) * 16] = c_im[g].T
    return dict(uT=np.ascontiguousarray(uc.T), u=uc,
                lam_re=np.ascontiguousarray(prm['s5_lambda_re'][gs].reshape(-1)),
                lam_im=np.ascontiguousarray(prm['s5_lambda_im'][gs].reshape(-1)),
                lstep=np.ascontiguousarray(np.repeat(prm['s5_log_step'][gs], 64)),
                Bre=Bre, Bim=Bim, Cre=Cre, Cim=Cim, dsk=np.ascontiguousarray(prm['s5_d'][cs]),
                triu=np.triu(np.ones((128, 128), np.float32)),
                iota_p=np.arange(128, dtype=np.float32).reshape(128, 1),
                iota_f=np.tile(np.arange(128, dtype=np.float32)[None], (128, 1)))


GELU_C = 1.5957691216057308


def gen_LRU(L, k):
    TT = 512
    NCH = L // TT
    xbT = k.din("xbT", [256, L])
    gateT = k.din("gateT", [256, L])
    cw_d = k.din("cw", [128, 2, 4])
    cb_d = k.din("cb", [128, 2])
    Wa_d = k.din("Wa", [2, 128, 128])
    Wx_d = k.din("Wx", [2, 128, 128])
    ba_d = k.din("ba", [128, 2])
    bx_d = k.din("bx", [128, 2])
    lam_d = k.din("lam", [128, 2])
    odT = k.dout("odT", [256, L])
    cw = k.sb("cw_s", [128, 2, 4])
    cb = k.sb("cb_s", [128, 2])
    Wa = k.sb("Wa_s", [128, 2, 128])
    Wx = k.sb("Wx_s", [128, 2, 128])
    ba = k.sb("ba_s", [128, 2])
    bx = k.sb("bx_s", [128, 2])
    c8 = k.sb("c8", [128, 2])
    k.dma('sp', cw[:], cw_d, w=['cw'])
    k.dma('sp', cb[:], cb_d, w=['cb'])
    k.dma('sp', Wa[:], Wa_d.rearrange("b p n -> p b n"), w=['Wa'])
    k.dma('sp', Wx[:], Wx_d.rearrange("b p n -> p b n"), w=['Wx'])
    k.dma('sp', ba[:], ba_d, w=['ba'])
    k.dma('sp', bx[:], bx_d, w=['bx'])
    k.dma('sp', c8[:], lam_d, w=['c8'])
    k.act(c8[:], c8[:], AF.Exp, ['c8'], ['c8'], scale=-1.0)
    k.act(c8[:], c8[:], AF.Ln, ['c8'], ['c8'], bias=1.0)
    k.ts('dve', c8[:], c8[:], -8.0, None, ALU.mult, None, ['c8'], ['c8'])
    hlast = k.sb("hlast", [128, 2])
    k.memset('dve', hlast[:], 0.0, ['hlast0', 'hlast1'])

    def ring(nm, shape, n):
        return [k.sb(f"{nm}{j}", shape) for j in range(n)]
    xh = ring("xh", [128, TT + 3], 3)
    gt = ring("gt", [128, TT], 8)
    xc = ring("xc", [128, TT], 5)
    r, ig, a, a2 = ring("r", [128, TT], 2), ring("ig", [128, TT], 3), ring("a", [128, TT], 5), ring("a2", [128, TT], 3)
    bt = ring("bt", [128, TT], 4)
    g2 = ring("g2", [128, TT], 5)
    h = ring("h", [128, TT], 2)
    ot = ring("ot", [128, TT], 3)
    psR = k.ps("psR", [128, TT])
    psI = k.ps("psI", [128, TT])

    def item(n):
        c, pb = divmod(n, 2)
        prow = slice(pb * 128, (pb + 1) * 128)
        def T(lst, nm):
            j = n % len(lst)
            return lst[j], f'{nm}{j}'
        xh_, kxh = T(xh, 'xh'); gt_, kgt = T(gt, 'gt'); xc_, kxc = T(xc, 'xc'); r_, kr = T(r, 'r'); ig_, kig = T(ig, 'ig')
        a_, ka = T(a, 'a'); a2_, ka2 = T(a2, 'a2'); bt_, kbt = T(bt, 'bt'); g2_, kg2 = T(g2, 'g2'); h_, kh = T(h, 'h'); ot_, kot = T(ot, 'ot')
        if c == 0:
            k.memset('dve', xh_[:, 0:3], 0.0, [kxh + 'h'])
            k.dma('sp', xh_[:, 3:TT + 3], xbT[prow, 0:TT], w=[kxh])
        else:
            k.dma('sp', xh_[:, 0:TT + 3], xbT[prow, c * TT - 3:(c + 1) * TT], w=[kxh, kxh + 'h'])
        k.dma('sp', gt_[:], gateT[prow, c * TT:(c + 1) * TT], w=[kgt])
        yield
        xk = [kxh, kxh + 'h']
        k.ts('dve', xc_[:], xh_[:, 3:TT + 3], cw[:, pb, 3:4], cb[:, pb:pb + 1], ALU.mult, ALU.add, xk + ['cw', 'cb'], [kxc])
        for j in (2, 1, 0):
            k.stt(xc_[:], xh_[:, j:j + TT], cw[:, pb, j:j + 1], xc_[:], ALU.mult, ALU.add, xk + ['cw', kxc], [kxc])
        yield
        k.mm(psR[:], Wa[:, pb, :], xc_[:], True, True, ['Wa', kxc], ['psR'])
        k.mm(psI[:], Wx[:, pb, :], xc_[:], True, True, ['Wx', kxc], ['psI'])
        yield
        k.act(r_[:], psR[:], AF.Sigmoid, ['psR', 'ba'], [kr], bias=ba[:, pb:pb + 1])
        k.act(ig_[:], psI[:], AF.Sigmoid, ['psI', 'bx'], [kig], bias=bx[:, pb:pb + 1])
        k.act(a_[:], r_[:], AF.Exp, [kr, 'c8'], [ka], scale=c8[:, pb:pb + 1])
        k.act(a2_[:], a_[:], AF.Square, [ka], [ka2])
        k.act(a2_[:], a2_[:], AF.Sqrt, [ka2], [ka2], scale=-1.0, bias=1.0)
        k.act(g2_[:], gt_[:], AF.Square, [kgt], [kg2])
        k.act(g2_[:], g2_[:], AF.Copy, [kg2], [kg2], scale=0.044715, bias=1.0)
        yield
        k.tt('dve', bt_[:], ig_[:], xc_[:], ALU.mult, [kig, kxc], [kbt])
        k.tt('dve', bt_[:], bt_[:], a2_[:], ALU.mult, [kbt, ka2], [kbt])
        k.tt('dve', g2_[:], g2_[:], gt_[:], ALU.mult, [kg2, kgt], [kg2])
        yield
        k.act(g2_[:], g2_[:], AF.Sigmoid, [kg2], [kg2], scale=GELU_C)
        yield
        k.P.op('dve', lambda e: e.tensor_tensor_scan(out=h_[:], data0=a_[:], data1=bt_[:], initial=hlast[:, pb:pb + 1],
                                                     op0=ALU.mult, op1=ALU.add),
               reads=[ka, kbt, f'hlast{pb}'], writes=[kh])
        k.cp('dve', hlast[:, pb:pb + 1], h_[:, TT - 1:TT], [kh], [f'hlast{pb}'])
        k.tt('dve', g2_[:], g2_[:], gt_[:], ALU.mult, [kg2, kgt], [kg2])
        k.tt('dve', ot_[:], h_[:], g2_[:], ALU.mult, [kh, kg2], [kot])
        yield
        k.dma('pool', odT[prow, c * TT:(c + 1) * TT], ot_[:], r=[kot], final=True)

    yield from pipeline_gen(item, 2 * NCH)


def build_LRU(L, k=None):
    k = k or K()
    for _ in gen_LRU(L, k):
        pass
    return k.finish()


def lru_host_inputs(s, xb, gate, prm):
    cs = slice(256 * s, 256 * s + 256)
    col = lambda v: np.ascontiguousarray(v[cs].reshape(2, 128).T)
    Wa = np.zeros((2, 128, 128), np.float32)
    Wx = np.zeros((2, 128, 128), np.float32)
    for pb in range(2):
        for bl in range(2):
            blk = 4 * s + 2 * pb + bl
            Wa[pb, bl * 64:(bl + 1) * 64, bl * 64:(bl + 1) * 64] = prm['lru_w_a'][blk]
            Wx[pb, bl * 64:(bl + 1) * 64, bl * 64:(bl + 1) * 64] = prm['lru_w_x'][blk]
    cw = np.ascontiguousarray(prm['lru_conv_w'][:, cs].reshape(4, 2, 128).transpose(2, 1, 0))
    return dict(xbT=np.ascontiguousarray(xb[:, cs].T), gateT=np.ascontiguousarray(gate[:, cs].T), cw=cw,
                cb=col(prm['lru_conv_b']), Wa=Wa, Wx=Wx, ba=col(prm['lru_b_a']), bx=col(prm['lru_b_x']),
                lam=col(prm['lru_lambda']))


GN_EPS = 64e-5
NLEV = 5


def build_RWKV(L, k=None, NH=4, fr=False, CH=64):
    k = k or K()
    NT = L // 128
    W = NH * 64
    NG = NH // 4
    FR = mybir.dt.float32r if fr else F32
    rd = (lambda ap: ap.bitcast(F32)) if fr else (lambda ap: ap)
    NCK = 128 // CH
    nlev = 5 if CH == 64 else 6
    frc = fr and CH == 128
    FRC = mybir.dt.float32r if frc else F32
    rdc = (lambda ap: ap.bitcast(F32)) if frc else (lambda ap: ap)
    lhc = (lambda ap: ap) if frc else rd
    prkv = [k.din(nm, [L, W]) for nm in ("pr", "pk", "pv")]
    mu1 = k.din("mu1", [3 * W])
    pls = [k.din("plw", [64, L]), k.din("pla", [64, L]), k.din("plg", [128, L])]
    mul = k.din("mul", [128, 3])
    w2 = k.din("w2", [64, W])
    a2 = k.din("a2", [64, W])
    g2 = k.din("g2", [128, W])
    vecs = k.din("vecs", [7, W])
    ident_d = k.din("ident", [128, 128])
    triw_d = k.din("triw", [3, 128, 128])
    mask5_d = k.din("mask5", [128, 640])
    rowm_d = k.din("rowm", [128, 2])
    oc = k.dout("oc", [L, W])

    k.consts(ident_d)
    triw = k.sb("triw_s", [128, 3, 128])
    k.dma('sp', triw[:], triw_d.rearrange("a p n -> p a n"), w=['triw'])
    mask5 = k.sb("mask5_s", [128, 640])
    k.dma('sp', mask5[:], mask5_d, w=['mask5'])
    rowm = k.sb("rowm_s", [128, 2])
    k.dma('sp', rowm[:], rowm_d, w=['rowm'])
    mu1bc = k.bcast_row("mu1bc", mu1, 3 * W)
    vb = [k.bcast_row(f"vb{i}", vecs[i], W) for i in range(7)]
    w0bc, a0bc, kkbc, kabc, rkbc, lngbc, lnbbc = vb
    VK = [f"vb{i}" for i in range(7)]
    muls = k.sb("muls", [128, 3])
    k.dma('sp', muls[:], mul, w=['muls'])
    w2s = k.sb("w2s", [64, W])
    a2s = k.sb("a2s", [64, W])
    k.dma('sp', w2s[:], w2, w=['w2s'])
    k.dma('sp', a2s[:], a2, w=['a2s'])
    g2s = k.sb("g2s", [128, W])
    k.dma('sp', g2s[:], g2, w=['g2s'])
    ST = [k.sb(f"ST{i}", [64, 64], FRC) for i in range(NH)]
    zt = k.sb("zt", [128, W])
    k.memset('dve', zt[:], 0.0, ['zt'])
    for i in range(NH):
        k.cp('dve', ST[i][:], zt[0:64, 0:64], ['zt'], [f'ST{i}'])
    P1s = k.sb("P1s", [128, W], FRC)
    Us = k.sb("Us", [128, W], FRC)
    k.cp('dve', P1s[:], zt[:], ['zt'], ['P1s'])
    k.cp('dve', Us[:], zt[:], ['zt'], ['Us'])

    pt = [k.sb(f"pt{i}", [128, 3 * W]) for i in range(2)]
    pp = [k.sb(f"pp{i}", [128, 3 * W]) for i in range(2)]
    lt = [k.sb(f"lt{i}", [128, 3, 128]) for i in range(2)]
    lp = [k.sb(f"lp{i}", [128, 3, 128]) for i in range(2)]
    for i_ in range(2):
        k.memset('pool', lt[i_][:], 0.0, [f'lt{i_}0', f'lt{i_}1', f'lt{i_}2'])
        k.memset('pool', lp[i_][:], 0.0, [f'lp{i_}0', f'lp{i_}1', f'lp{i_}2', f'lp{i_}z'])
    pm = k.sb("pm", [128, 3 * W])
    vr = k.sb("vr", [128, W], FR)
    lm = k.sb("lm", [128, 3, 128])
    sw = k.sb("sw", [128, W])
    av = k.sb("av", [128, W])
    gv = k.sb("gv", [128, W])
    kkr = k.sb("kkr", [128, W])
    sq = k.sb("sq", [128, W])
    s4 = k.sb("s4", [128, NH])
    rn = k.sb("rn", [128, NH])
    nkk = k.sb("nkk", [128, W])
    kmod = k.sb("kmod", [128, W])
    kka = k.sb("kka", [128, W])
    tmp = k.sb("tmp", [128, W])
    bon = k.sb("bon", [128, NH])
    E1 = k.sb("E1", [128, W])
    E2 = k.sb("E2", [128, W])
    E3 = k.sb("E3", [128, W])
    E4 = k.sb("E4", [128, W])
    E1T = k.sb("E1T", [64, NH, 128])
    At = k.sb("At", [128, W])
    Bs = k.sb("Bs", [128, W])
    Ks = k.sb("Ks", [128, W])
    Rt = k.sb("Rt", [128, W])
    Bfm = [k.sb(f"Bfm{c}", [128, W]) for c in range(2)]
    Kfm = [k.sb(f"Kfm{c}", [128, W]) for c in range(2)]
    FT = [k.sb(f"FT{h}", [64, 4, 128], FR) for h in range(NH)]
    A5 = [k.sb(f"A5_{h}", [128, 640], FR) for h in range(NH)]
    NL = [k.sb(f"NL_{h}", [128, 256], FR) for h in range(NH)]
    PQ = [k.sb(f"PQ_{h}", [128, 256], FR) for h in range(NH)]
    W1 = k.sb("W1", [128, W], FR)
    U1 = k.sb("U1", [128, W])
    ysb = k.sb("ysb", [128, W])
    yc = k.sb("yc", [128, W])
    m4 = k.sb("m4", [128, NH])
    r4 = k.sb("r4", [128, NH])
    ot = [k.sb(f"ot{i}", [128, W]) for i in range(2)]
    B = [k.ps(f"psB{i}", [128, 512]) for i in range(8)]
    bk = lambda i: f'psB{i}'
    v3 = lambda t: t.rearrange("p (h j) -> p h j", h=NH)
    bc4 = lambda t: t.unsqueeze(2).broadcast_to([128, NH, 64])

    for i in range(NT):
        b = i % 2
        rows = slice(i * 128, (i + 1) * 128)
        PK, PPK, LTK, LPK = [], [], [], []
        for q in range(3):
            cq = slice(q * W, (q + 1) * W)
            k.dma('sp', pt[b][:, cq], prkv[q][rows, :], w=[f'pt{b}{q}'])
            PK.append(f'pt{b}{q}')
            if i == 0:
                k.dma('sp', pp[b][1:128, cq], prkv[q][0:127, :], w=[f'pp{b}{q}'])
            else:
                k.dma('sp', pp[b][:, cq], prkv[q][i * 128 - 1:i * 128 + 127, :], w=[f'pp{b}{q}'])
            PPK.append(f'pp{b}{q}')
            nr = pls[q].shape[0]
            k.dma('sp', lt[b][0:nr, q, :], pls[q][:, rows], w=[f'lt{b}{q}'])
            LTK.append(f'lt{b}{q}')
            if i == 0:
                k.dma('sp', lp[b][0:nr, q, 1:128], pls[q][:, 0:127], w=[f'lp{b}{q}'])
            else:
                k.dma('sp', lp[b][0:nr, q, :], pls[q][:, i * 128 - 1:i * 128 + 127], w=[f'lp{b}{q}'])
            LPK.append(f'lp{b}{q}')
        if i == 0:
            k.memset('pool', pp[b][0:1, :], 0.0, [f'pp{b}z'])
            k.memset('pool', lp[b][:, :, 0:1], 0.0, [f'lp{b}z'])
            PPK.append(f'pp{b}z')
            LPK.append(f'lp{b}z')
        k.tt('pool', pm[:], pp[b][:], pt[b][:], ALU.subtract, PPK + PK, ['pm'])
        k.tt('pool', pm[:], pm[:], mu1bc[:], ALU.mult, ['pm', 'mu1bc'], ['pm'])
        k.tt('pool', pm[:], pm[:], pt[b][:], ALU.add, ['pm'] + PK, ['pm'])
        r_, k_, v_ = pm[:, 0:W], pm[:, W:2 * W], pm[:, 2 * W:3 * W]
        k.cp('act', vr[:], v_, ['pm'], ['vr'])
        LK = LTK + LPK
        k.tt('dve', lm[:], lp[b][:], lt[b][:], ALU.subtract, LK, ['lm'])
        for blk in range(3):
            k.stt(lm[:, blk, :], lm[:, blk, :], muls[:, blk:blk + 1], lt[b][:, blk, :], ALU.mult, ALU.add,
                  ['lm', 'muls'] + LK, ['lm'])
        k.act(lm[0:64, 0, :], lm[0:64, 0, :], AF.Tanh, ['lm'], ['lm'])
        k.act(lm[:, 2, :], lm[:, 2, :], AF.Sigmoid, ['lm'], ['lm'])
        k.mm(B[0][:, 0:W], lm[0:64, 0, :], w2s[:], True, True, ['lm', 'w2s'], [bk(0)])
        k.mm(B[1][:, 0:W], lm[0:64, 1, :], a2s[:], True, True, ['lm', 'a2s'], [bk(1)])
        k.mm(B[2][:, 0:W], lm[:, 2, :], g2s[:], True, True, ['lm', 'g2s'], [bk(2)])
        k.tt('dve', sw[:], B[0][:, 0:W], w0bc[:], ALU.add, [bk(0), VK[0]], ['sw'])
        k.act(sw[:], sw[:], AF.Sigmoid, ['sw'], ['sw'])
        k.tt('dve', av[:], B[1][:, 0:W], a0bc[:], ALU.add, [bk(1), VK[1]], ['av'])
        k.act(av[:], av[:], AF.Sigmoid, ['av'], ['av'])
        k.cp('act', gv[:], B[2][:, 0:W], [bk(2)], ['gv'])
        k.tt('pool', kkr[:], k_, kkbc[:], ALU.mult, ['pm', VK[2]], ['kkr'])
        k.tt('pool', sq[:], kkr[:], kkr[:], ALU.mult, ['kkr'], ['sq'])
        k.P.op('dve', lambda e: e.tensor_reduce(out=s4[:], in_=v3(sq[:]), axis=AX.X, op=ALU.add), reads=['sq'], writes=['s4'])
        k.act(s4[:], s4[:], AF.Sqrt, ['s4'], ['s4'])
        k.ts('dve', s4[:], s4[:], 1e-12, None, ALU.max, None, ['s4'], ['s4'])
        k.recip(rn[:], s4[:], ['s4'], ['rn'])
        k.ts('dve', rn[:], rn[:], -1.0, None, ALU.mult, None, ['rn'], ['rn'])
        k.tt('dve', v3(nkk[:]), v3(kkr[:]), bc4(rn[:]), ALU.mult, ['kkr', 'rn'], ['nkk'])
        k.stt(tmp[:], av[:], -1.0, kabc[:], ALU.add, ALU.mult, ['av', VK[3]], ['tmp'])
        k.stt(kmod[:], tmp[:], 1.0, k_, ALU.add, ALU.mult, ['tmp', 'pm'], ['kmod'])
        k.stt(kka[:], nkk[:], -1.0, av[:], ALU.mult, ALU.mult, ['nkk', 'av'], ['kka'])
        k.tt('pool', tmp[:], r_, kmod[:], ALU.mult, ['pm', 'kmod', 'tmp'], ['tmp'])
        k.tt('pool', tmp[:], tmp[:], rkbc[:], ALU.mult, ['tmp', VK[4]], ['tmp'])
        k.P.op('dve', lambda e: e.tensor_reduce(out=bon[:], in_=v3(tmp[:]), axis=AX.X, op=ALU.add), reads=['tmp'], writes=['bon'])
        k.mm(B[3][:, 0:W], triw[:, 0, :], sw[:], True, True, ['triw', 'sw'], [bk(3)])
        k.mm(B[4][:, 0:W], triw[:, 1, :], sw[:], True, True, ['triw', 'sw'], [bk(4)])
        k.mm(B[5][:, 0:W], triw[:, 2, :], sw[:], True, True, ['triw', 'sw'], [bk(5)])
        for h in range(NH):
            k.mm(B[6 + h // 4][0:64, (h % 4) * 128:(h % 4 + 1) * 128], sw[:, h * 64:(h + 1) * 64], triw[:, 0, :], True, True,
                 ['sw', 'triw'], [bk(6 + h // 4)])
        k.act(E1[:], B[3][:, 0:W], AF.Exp, [bk(3)], ['E1'])
        k.act(E2[:], B[3][:, 0:W], AF.Exp, [bk(3)], ['E2'], scale=-1.0)
        k.act(E3[:], B[4][:, 0:W], AF.Exp, [bk(4)], ['E3'])
        k.act(E4[:], B[5][:, 0:W], AF.Exp, [bk(5)], ['E4'])
        for g in range(NG):
            k.act(E1T[:, 4 * g:4 * g + 4, :].rearrange("p a t -> p (a t)"), B[6 + g][0:64, :], AF.Exp, [bk(6 + g)], ['E1T'])
        k.tt('dve', At[:], nkk[:], E3[:], ALU.mult, ['nkk', 'E3'], ['At'])
        k.tt('pool', Bs[:], kka[:], E2[:], ALU.mult, ['kka', 'E2'], ['Bs'])
        k.tt('dve', Ks[:], kmod[:], E2[:], ALU.mult, ['kmod', 'E2'], ['Ks'])
        k.tt('pool', Rt[:], r_, E1[:], ALU.mult, ['pm', 'E1'], ['Rt'])
        for c in range(NCK):
            k.stt(Bfm[c][:], kka[:], rowm[:, c:c + 1], E4[:], ALU.mult, ALU.mult, ['kka', 'E4', 'rowm'], [f'Bfm{c}'])
            k.stt(Kfm[c][:], kmod[:], rowm[:, c:c + 1], E4[:], ALU.mult, ALU.mult, ['kmod', 'E4', 'rowm'], [f'Kfm{c}'])
        HS = list(range(NH))
        for h in HS:
            cs_ = slice(h * 64, (h + 1) * 64)
            for q, (src, key) in enumerate([(At, 'At'), (Bs, 'Bs'), (Ks, 'Ks'), (Rt, 'Rt')]):
                k.tr(B[h][0:64, q * 128:(q + 1) * 128], src[:, cs_], k.identf[:], [key], [bk(h)])
        for h in HS:
            k.cp('act' if h % 2 else 'dve', FT[h][:].rearrange("p a t -> p (a t)"), B[h][0:64, :], [bk(h)], [f'FT{h}'])
        for h in HS:
            AtT, BsT, KsT, RtT = (FT[h][:, q, :] for q in range(4))
            o = lambda j: B[h][:, j * 128:(j + 1) * 128]
            k.mm(o(0), BsT, AtT, True, True, [f'FT{h}'], [bk(h)])
            k.mm(o(1), AtT, BsT, True, True, [f'FT{h}'], [bk(h)])
            k.mm(o(2), KsT, AtT, True, True, [f'FT{h}'], [bk(h)])
        for h in HS:
            k.tt('dve', A5[h][:, 0:384], B[h][:, 0:384], mask5[:, 0:384], ALU.mult, [bk(h), 'mask5'], [f'A5_{h}'])
        for h in HS:
            AtT, BsT, KsT, RtT = (FT[h][:, q, :] for q in range(4))
            k.mm(B[h][:, 0:128], BsT, RtT, True, True, [f'FT{h}'], [bk(h)])
            k.mm(B[h][:, 128:256], KsT, RtT, True, True, [f'FT{h}'], [bk(h)])
        for h in HS:
            k.tt('dve', A5[h][:, 384:640], B[h][:, 0:256], mask5[:, 384:640], ALU.mult, [bk(h), 'mask5'], [f'A5b_{h}'])
            k.cp('act', NL[h][:], rd(A5[h][:, 0:256]), [f'A5_{h}'], [f'NL_{h}'])
            k.tt('pool' if not fr else 'dve', PQ[h][:].rearrange("p (a n) -> p a n", a=2), rd(A5[h][:, 0:256]).rearrange("p (a n) -> p a n", a=2),
                 k.identf[:].unsqueeze(1).broadcast_to([128, 2, 128]), ALU.add, [f'A5_{h}', 'ident'], [f'PQ_{h}'])
        for lev in range(nlev):
            for h in HS:
                N_, L_ = NL[h][:, 0:128], NL[h][:, 128:256]
                k.mm(B[h][:, 0:128], L_, N_, True, True, [f'NL_{h}'], [bk(h)])
                k.mm(B[h][:, 128:256], N_, L_, True, True, [f'NL_{h}'], [bk(h)])
            for h in HS:
                k.cp('act', NL[h][:], B[h][:, 0:256], [bk(h)], [f'NL_{h}'])
            for h in HS:
                N_, L_ = NL[h][:, 0:128], NL[h][:, 128:256]
                P_, Q_ = PQ[h][:, 0:128], PQ[h][:, 128:256]
                k.mm(B[h][:, 256:384], Q_, N_, True, True, [f'NL_{h}', f'PQ_{h}'], [bk(h)])
                k.mm(B[h][:, 384:512], P_, L_, True, True, [f'NL_{h}', f'PQ_{h}'], [bk(h)])
            for h in HS:
                k.tt('dve', PQ[h][:], B[h][:, 256:512], rd(PQ[h][:]), ALU.add, [bk(h), f'PQ_{h}'], [f'PQ_{h}'])
        for h in range(NH):
            k.mm(B[0][:, h * 64:(h + 1) * 64], A5[h][:, 256:384], vr[:, h * 64:(h + 1) * 64], True, True, [f'A5_{h}', 'vr'], [bk(0)])
        k.cp('act', W1[:], B[0][:, 0:W], [bk(0)], ['W1'])
        for h in range(NH):
            k.mm(B[1][:, h * 64:(h + 1) * 64], PQ[h][:, 0:128], W1[:, h * 64:(h + 1) * 64], True, True,
                 [f'PQ_{h}', 'W1'], [bk(1)])
        k.cp('act', U1[:], B[1][:, 0:W], [bk(1)], ['U1'])
        vsrc = vr if frc else None
        for c in range(NCK):
            cr = slice(c * CH, (c + 1) * CH)
            for h in range(NH):
                k.mm(B[2][cr, h * 64:(h + 1) * 64], lhc(FT[h][:, 0, cr]), ST[h][:], True, True, [f'FT{h}', f'ST{h}'], [bk(2)])
            k.cp('act', P1s[cr, :], B[2][cr, 0:W], [bk(2)], ['P1s'])
            for h in range(NH):
                k.mm(B[3][cr, h * 64:(h + 1) * 64], lhc(PQ[h][:, cr]), P1s[:, h * 64:(h + 1) * 64], True, True,
                     [f'PQ_{h}', 'P1s'], [bk(3)])
            k.tt('dve', Us[cr, :], B[3][cr, 0:W], U1[cr, :], ALU.add, [bk(3), 'U1'], ['Us'])
            for h in range(NH):
                hc_ = slice(h * 64, (h + 1) * 64)
                vh = vr[:, hc_] if frc else pm[:, 2 * W + h * 64:2 * W + (h + 1) * 64]
                vk = 'vr' if frc else 'pm'
                k.mm(B[6][cr, hc_], lhc(FT[h][:, 3, cr]), ST[h][:], True, False, [f'FT{h}', f'ST{h}'], [bk(6)])
                k.mm(B[6][cr, hc_], lhc(A5[h][:, 384:512][:, cr]), Us[:, hc_], False, False, [f'A5b_{h}', 'Us'], [bk(6)])
                k.mm(B[6][cr, hc_], lhc(A5[h][:, 512:640][:, cr]), vh, False, True, [f'A5b_{h}', vk], [bk(6)])
            for h in range(NH):
                hc_ = slice(h * 64, (h + 1) * 64)
                vh = pm[:, 2 * W + h * 64:2 * W + (h + 1) * 64]
                k.mm(B[7][0:64, hc_], Bfm[c][:, hc_], rdc(Us[:, hc_]), True, False, [f'Bfm{c}', 'Us'], [bk(7)])
                k.mm(B[7][0:64, hc_], Kfm[c][:, hc_], vh, False, True, [f'Kfm{c}', 'pm'], [bk(7)])
            for h in range(NH):
                hc_ = slice(h * 64, (h + 1) * 64)
                k.stt(ST[h][:], rdc(ST[h][:]), E1T[:, h, (c + 1) * CH - 1:(c + 1) * CH], B[7][0:64, hc_], ALU.mult, ALU.add,
                      [f'ST{h}', 'E1T', bk(7)], [f'ST{h}'])
        k.cp('act', ysb[:], B[6][:, 0:W], [bk(6)], ['ysb'])
        k.P.op('dve', lambda e: e.tensor_reduce(out=m4[:], in_=v3(ysb[:]), axis=AX.X, op=ALU.add), reads=['ysb'], writes=['m4'])
        k.ts('dve', m4[:], m4[:], -1.0 / 64.0, None, ALU.mult, None, ['m4'], ['m4'])
        k.tt('dve', v3(yc[:]), v3(ysb[:]), bc4(m4[:]), ALU.add, ['ysb', 'm4'], ['yc'])
        k.tt('pool', sq[:], yc[:], yc[:], ALU.mult, ['yc'], ['sq'])
        k.P.op('dve', lambda e: e.tensor_reduce(out=r4[:], in_=v3(sq[:]), axis=AX.X, op=ALU.add), reads=['sq'], writes=['r4'])
        k.ts('dve', r4[:], r4[:], 1.0 / 64.0, GN_EPS, ALU.mult, ALU.add, ['r4'], ['r4'])
        k.act(r4[:], r4[:], AF.Sqrt, ['r4'], ['r4'])
        k.recip(r4[:], r4[:], ['r4'], ['r4'])
        k.tt('dve', v3(yc[:]), v3(yc[:]), bc4(r4[:]), ALU.mult, ['yc', 'r4'], ['yc'])
        k.tt('pool', yc[:], yc[:], lngbc[:], ALU.mult, ['yc', VK[5]], ['yc'])
        k.tt('pool', yc[:], yc[:], lnbbc[:], ALU.add, ['yc', VK[6]], ['yc'])
        k.tt('dve', v3(tmp[:]), v3(v_), bc4(bon[:]), ALU.mult, ['pm', 'bon', 'tmp'], ['tmp'])
        k.tt('pool', yc[:], yc[:], tmp[:], ALU.add, ['yc', 'tmp'], ['yc'])
        k.tt('dve', ot[b][:], yc[:], gv[:], ALU.mult, ['yc', 'gv'], [f'ot{b}'])
        k.dma('pool', oc[rows, :], ot[b][:], r=[f'ot{b}'], final=True)
    return k.finish()


def build_RWKVP(L, k=None, CH=64):
    NH, fr = 8, True
    k = k or K()
    NT = L // 128
    W = NH * 64
    NG = NH // 4
    FR = mybir.dt.float32r if fr else F32
    rd = (lambda ap: ap.bitcast(F32)) if fr else (lambda ap: ap)
    NCK = 128 // CH
    nlev = 5 if CH == 64 else 6
    frc = True
    FRC = mybir.dt.float32r if frc else F32
    rdc = (lambda ap: ap.bitcast(F32)) if frc else (lambda ap: ap)
    lhc = (lambda ap: ap) if frc else rd
    prkv = [k.din(nm, [L, W]) for nm in ("pr", "pk", "pv")]
    mu1 = k.din("mu1", [3 * W])
    pls = [k.din("plw", [64, L]), k.din("pla", [64, L]), k.din("plg", [128, L])]
    mul = k.din("mul", [128, 3])
    w2 = k.din("w2", [64, W])
    a2 = k.din("a2", [64, W])
    g2 = k.din("g2", [128, W])
    vecs = k.din("vecs", [7, W])
    ident_d = k.din("ident", [128, 128])
    triw_d = k.din("triw", [3, 128, 128])
    mask5_d = k.din("mask5", [128, 640])
    rowm_d = k.din("rowm", [128, 2])
    oc = k.dout("oc", [L, W])

    k.consts(ident_d)
    triw = k.sb("triw_s", [128, 3, 128])
    k.dma('sp', triw[:], triw_d.rearrange("a p n -> p a n"), w=['triw'])
    mask5 = k.sb("mask5_s", [128, 640])
    k.dma('sp', mask5[:], mask5_d, w=['mask5'])
    rowm = k.sb("rowm_s", [128, 2])
    k.dma('sp', rowm[:], rowm_d, w=['rowm'])
    mu1bc = k.bcast_row("mu1bc", mu1, 3 * W)
    vb = [k.bcast_row(f"vb{i}", vecs[i], W) for i in range(7)]
    w0bc, a0bc, kkbc, kabc, rkbc, lngbc, lnbbc = vb
    VK = [f"vb{i}" for i in range(7)]
    muls = k.sb("muls", [128, 3])
    k.dma('sp', muls[:], mul, w=['muls'])
    w2s = k.sb("w2s", [64, W])
    a2s = k.sb("a2s", [64, W])
    k.dma('sp', w2s[:], w2, w=['w2s'])
    k.dma('sp', a2s[:], a2, w=['a2s'])
    g2s = k.sb("g2s", [128, W])
    k.dma('sp', g2s[:], g2, w=['g2s'])
    ST = [k.sb(f"ST{i}", [64, 64], FRC) for i in range(NH)]
    zt = k.sb("zt", [128, W])
    k.memset('dve', zt[:], 0.0, ['zt'])
    for i in range(NH):
        k.cp('dve', ST[i][:], zt[0:64, 0:64], ['zt'], [f'ST{i}'])
    P1s = k.sb("P1s", [128, W], FRC)
    Us = k.sb("Us", [128, W], FRC)
    k.cp('dve', P1s[:], zt[:], ['zt'], ['P1s'])
    k.cp('dve', Us[:], zt[:], ['zt'], ['Us'])

    pt = [k.sb("pt0", [128, 3 * W])] * 2
    pp = [k.sb("pp0", [128, 3 * W])] * 2
    lt = [k.sb("lt0", [128, 3, 128])] * 2
    lp = [k.sb("lp0", [128, 3, 128])] * 2
    k.memset('pool', lt[0][:], 0.0, ['lt0', 'lt1', 'lt2'])
    k.memset('pool', lp[0][:], 0.0, ['lp0', 'lp1', 'lp2', 'lpz'])
    pm2 = [k.sb(f"pm{i_}", [128, 3 * W]) for i_ in range(2)]
    vr2 = [k.sb(f"vr{i_}", [128, W], FR) for i_ in range(2)]
    lm2 = [k.sb(f"lm{i_}", [128, 3, 128]) for i_ in range(2)]
    sw = k.sb("sw", [128, W])
    av = k.sb("av", [128, W])
    gv2 = [k.sb(f"gv{i_}", [128, W]) for i_ in range(2)]
    kkr = k.sb("kkr", [128, W])
    sq = k.sb("sq", [128, W])
    s4 = k.sb("s4", [128, NH])
    rn = k.sb("rn", [128, NH])
    nkk = k.sb("nkk", [128, W])
    kmod = k.sb("kmod", [128, W])
    kka = k.sb("kka", [128, W])
    tmp = k.sb("tmp", [128, W])
    bon2 = [k.sb(f"bon{i_}", [128, NH]) for i_ in range(2)]
    E1 = k.sb("E1", [128, W])
    E2 = k.sb("E2", [128, W])
    E3 = k.sb("E3", [128, W])
    E4 = k.sb("E4", [128, W])
    E1T2 = [k.sb(f"E1T{i_}", [64, NH, 128]) for i_ in range(2)]
    At2 = [k.sb(f"At{i_}", [128, W]) for i_ in range(2)]
    Bs2 = [k.sb(f"Bs{i_}", [128, W]) for i_ in range(2)]
    Ks2 = [k.sb(f"Ks{i_}", [128, W]) for i_ in range(2)]
    Rt2 = [k.sb(f"Rt{i_}", [128, W]) for i_ in range(2)]
    Bfm2 = [[k.sb(f"Bfm{p_}{c}", [128, W]) for c in range(NCK)] for p_ in range(2)]
    Kfm2 = [[k.sb(f"Kfm{p_}{c}", [128, W]) for c in range(NCK)] for p_ in range(2)]
    sqp = k.sb("sqp", [128, W])
    tmpp = k.sb("tmpp", [128, W])
    FT = [k.sb(f"FT{h}", [64, 4, 128], FR) for h in range(NH)]
    A5 = [k.sb(f"A5_{h}", [128, 640], FR) for h in range(NH)]
    NL = [k.sb(f"NL_{h}", [128, 256], FR) for h in range(NH)]
    PQ = [k.sb(f"PQ_{h}", [128, 128], FR) for h in range(NH)]
    W1 = k.sb("W1", [128, W], FR)
    U1 = k.sb("U1", [128, W])
    ysb = k.sb("ysb", [128, W])
    yc = k.sb("yc", [128, W])
    m4 = k.sb("m4", [128, NH])
    r4 = k.sb("r4", [128, NH])
    ot = [k.sb(f"ot{i}", [128, W]) for i in range(2)]
    B = [k.ps(f"psB{i}", [128, 512]) for i in range(8)]
    bk = lambda i: f'psB{i}'
    v3 = lambda t: t.rearrange("p (h j) -> p h j", h=NH)
    bc4 = lambda t: t.unsqueeze(2).broadcast_to([128, NH, 64])


    S0, S1, C0, C1 = 6, 7, 4, 5

    def tile(i):
        b = i % 2
        pm, lm = pm2[b], lm2[b]
        kpm, klm = f'pm{b}', f'lm{b}'
        At, Bs, Ks, Rt, gv, vr, bon, E1T, Bf, Kf = At2[b], Bs2[b], Ks2[b], Rt2[b], gv2[b], vr2[b], bon2[b], E1T2[b], Bfm2[b], Kfm2[b]
        kAt, kBs, kKs, kRt, kgv, kvr, kbon, kE1T, kBf, kKf = (f'{n_}{b}' for n_ in ('At', 'Bs', 'Ks', 'Rt', 'gv', 'vr', 'bon', 'E1T', 'Bf', 'Kf'))
        rows = slice(i * 128, (i + 1) * 128)
        PK, PPK, LTK, LPK = [], [], [], []
        for q in range(3):
            cq = slice(q * W, (q + 1) * W)
            k.dma('sp', pt[b][:, cq], prkv[q][rows, :], w=[f'pt{q}'])
            PK.append(f'pt{q}')
            if i == 0:
                k.dma('sp', pp[b][1:128, cq], prkv[q][0:127, :], w=[f'pp{q}'])
            else:
                k.dma('sp', pp[b][:, cq], prkv[q][i * 128 - 1:i * 128 + 127, :], w=[f'pp{q}'])
            PPK.append(f'pp{q}')
            nr = pls[q].shape[0]
            k.dma('sp', lt[b][0:nr, q, :], pls[q][:, rows], w=[f'lt{q}'])
            LTK.append(f'lt{q}')
            if i == 0:
                k.dma('sp', lp[b][0:nr, q, 1:128], pls[q][:, 0:127], w=[f'lp{q}'])
            else:
                k.dma('sp', lp[b][0:nr, q, :], pls[q][:, i * 128 - 1:i * 128 + 127], w=[f'lp{q}'])
            LPK.append(f'lp{q}')
        if i == 0:
            k.memset('pool', pp[b][0:1, :], 0.0, ['ppz'])
            k.memset('pool', lp[b][:, :, 0:1], 0.0, ['lpz'])
            PPK.append('ppz')
            LPK.append('lpz')
        k.tt('dve', pm[:], pp[b][:], pt[b][:], ALU.subtract, PPK + PK, [kpm])
        k.tt('dve', pm[:], pm[:], mu1bc[:], ALU.mult, [kpm, 'mu1bc'], [kpm])
        k.tt('dve', pm[:], pm[:], pt[b][:], ALU.add, [kpm] + PK, [kpm])
        r_, k_, v_ = pm[:, 0:W], pm[:, W:2 * W], pm[:, 2 * W:3 * W]
        LK = LTK + LPK
        k.tt('dve', lm[:], lp[b][:], lt[b][:], ALU.subtract, LK, [klm])
        for blk in range(3):
            k.stt(lm[:, blk, :], lm[:, blk, :], muls[:, blk:blk + 1], lt[b][:, blk, :], ALU.mult, ALU.add,
                  [klm, 'muls'] + LK, [klm])
        k.act(lm[0:64, 0, :], lm[0:64, 0, :], AF.Tanh, [klm], [klm])
        k.act(lm[:, 2, :], lm[:, 2, :], AF.Sigmoid, [klm], [klm])
        yield 'STAGE'
        k.cp('act', vr[:], v_, [kpm], [kvr])
        k.mm(B[S0][:, 0:W], lm[0:64, 0, :], w2s[:], True, True, [klm, 'w2s'], [bk(S0)])
        k.mm(B[S1][:, 0:W], lm[0:64, 1, :], a2s[:], True, True, [klm, 'a2s'], [bk(S1)])
        yield 'sub'
        k.tt('dve', sw[:], B[S0][:, 0:W], w0bc[:], ALU.add, [bk(S0), VK[0]], ['sw'])
        k.act(sw[:], sw[:], AF.Sigmoid, ['sw'], ['sw'])
        k.tt('dve', av[:], B[S1][:, 0:W], a0bc[:], ALU.add, [bk(S1), VK[1]], ['av'])
        k.act(av[:], av[:], AF.Sigmoid, ['av'], ['av'])
        yield 'sub'
        k.mm(B[S0][:, 0:W], lm[:, 2, :], g2s[:], True, True, [klm, 'g2s'], [bk(S0)])
        k.cp('act', gv[:], B[S0][:, 0:W], [bk(S0)], [kgv])
        yield 'sub'
        k.tt('dve', kkr[:], k_, kkbc[:], ALU.mult, [kpm, VK[2]], ['kkr'])
        k.tt('dve', sq[:], kkr[:], kkr[:], ALU.mult, ['kkr'], ['sq'])
        k.P.op('dve', lambda e: e.tensor_reduce(out=s4[:], in_=v3(sq[:]), axis=AX.X, op=ALU.add), reads=['sq'], writes=['s4'])
        k.act(s4[:], s4[:], AF.Sqrt, ['s4'], ['s4'])
        k.ts('dve', s4[:], s4[:], 1e-12, None, ALU.max, None, ['s4'], ['s4'])
        k.recip(rn[:], s4[:], ['s4'], ['rn'])
        k.ts('dve', rn[:], rn[:], -1.0, None, ALU.mult, None, ['rn'], ['rn'])
        k.tt('dve', v3(nkk[:]), v3(kkr[:]), bc4(rn[:]), ALU.mult, ['kkr', 'rn'], ['nkk'])
        k.stt(tmp[:], av[:], -1.0, kabc[:], ALU.add, ALU.mult, ['av', VK[3]], ['tmp'])
        k.stt(kmod[:], tmp[:], 1.0, k_, ALU.add, ALU.mult, ['tmp', kpm], ['kmod'])
        k.stt(kka[:], nkk[:], -1.0, av[:], ALU.mult, ALU.mult, ['nkk', 'av'], ['kka'])
        k.tt('dve', tmp[:], r_, kmod[:], ALU.mult, [kpm, 'kmod', 'tmp'], ['tmp'])
        k.tt('dve', tmp[:], tmp[:], rkbc[:], ALU.mult, ['tmp', VK[4]], ['tmp'])
        k.P.op('dve', lambda e: e.tensor_reduce(out=bon[:], in_=v3(tmp[:]), axis=AX.X, op=ALU.add), reads=['tmp'], writes=[kbon])
        yield 'sub'
        k.mm(B[S1][:, 0:W], triw[:, 0, :], sw[:], True, True, ['triw', 'sw'], [bk(S1)])
        k.mm(B[S0][:, 0:W], triw[:, 1, :], sw[:], True, True, ['triw', 'sw'], [bk(S0)])
        yield 'sub'
        k.act(E1[:], B[S1][:, 0:W], AF.Exp, [bk(S1)], ['E1'])
        k.act(E2[:], B[S1][:, 0:W], AF.Exp, [bk(S1)], ['E2'], scale=-1.0)
        k.act(E3[:], B[S0][:, 0:W], AF.Exp, [bk(S0)], ['E3'])
        k.mm(B[S1][:, 0:W], triw[:, 2, :], sw[:], True, True, ['triw', 'sw'], [bk(S1)])
        k.act(E4[:], B[S1][:, 0:W], AF.Exp, [bk(S1)], ['E4'])
        yield 'sub'
        for g in range(2):
            for hl in range(4):
                h = 4 * g + hl
                k.mm(B[S0 + g][0:64, hl * 128:(hl + 1) * 128], sw[:, h * 64:(h + 1) * 64], triw[:, 0, :], True, True,
                     ['sw', 'triw'], [bk(S0 + g)])
        yield 'sub'
        for g in range(2):
            k.act(E1T[:, 4 * g:4 * g + 4, :].rearrange("p a t -> p (a t)"), B[S0 + g][0:64, :], AF.Exp, [bk(S0 + g)], [kE1T])
        yield 'sub'
        k.tt('dve', At[:], nkk[:], E3[:], ALU.mult, ['nkk', 'E3'], [kAt])
        k.tt('dve', Bs[:], kka[:], E2[:], ALU.mult, ['kka', 'E2'], [kBs])
        k.tt('dve', Ks[:], kmod[:], E2[:], ALU.mult, ['kmod', 'E2'], [kKs])
        k.tt('dve', Rt[:], r_, E1[:], ALU.mult, [kpm, 'E1'], [kRt])
        for c in range(NCK):
            k.stt(Bf[c][:], kka[:], rowm[:, c:c + 1], E4[:], ALU.mult, ALU.mult, ['kka', 'E4', 'rowm'], [kBf])
            k.stt(Kf[c][:], kmod[:], rowm[:, c:c + 1], E4[:], ALU.mult, ALU.mult, ['kmod', 'E4', 'rowm'], [kKf])
        yield 'STAGE'
        for g in range(1):
            HS = list(range(8))
            for h in HS:
                hl = h
                cs_ = slice(h * 64, (h + 1) * 64)
                for q, (src, key) in enumerate([(At, kAt), (Bs, kBs), (Ks, kKs), (Rt, kRt)]):
                    k.tr(B[hl][0:64, q * 128:(q + 1) * 128], src[:, cs_], k.identf[:], [key], [bk(hl)])
            for h in HS:
                hl = h
                k.cp('act' if h % 2 else 'dve', FT[h][:].rearrange("p a t -> p (a t)"), B[hl][0:64, :], [bk(hl)], [f'FT{h}'])
            for h in HS:
                hl = h
                AtT, BsT, KsT, RtT = (FT[h][:, q, :] for q in range(4))
                k.mm(B[hl][:, 0:128], BsT, AtT, True, True, [f'FT{h}'], [bk(hl)])
                k.mm(B[hl][:, 128:256], AtT, BsT, True, True, [f'FT{h}'], [bk(hl)])
                k.mm(B[hl][:, 256:384], KsT, AtT, True, True, [f'FT{h}'], [bk(hl)])
            for h in HS:
                hl = h
                k.tt('dve', A5[h][:, 0:384], B[hl][:, 0:384], mask5[:, 0:384], ALU.mult, [bk(hl), 'mask5'], [f'A5_{h}'])
            for h in HS:
                hl = h
                AtT, BsT, KsT, RtT = (FT[h][:, q, :] for q in range(4))
                k.mm(B[hl][:, 0:128], BsT, RtT, True, True, [f'FT{h}'], [bk(hl)])
                k.mm(B[hl][:, 128:256], KsT, RtT, True, True, [f'FT{h}'], [bk(hl)])
            for h in HS:
                hl = h
                k.tt('dve', A5[h][:, 384:640], B[hl][:, 0:256], mask5[:, 384:640], ALU.mult, [bk(hl), 'mask5'], [f'A5b_{h}'])
                k.cp('act', NL[h][:], rd(A5[h][:, 0:256]), [f'A5_{h}'], [f'NL_{h}'])
                k.tt('dve', PQ[h][:, 0:128], rd(A5[h][:, 0:128]), k.identf[:], ALU.add, [f'A5_{h}', 'ident'], [f'PQ_{h}'])
            for lev in range(nlev):
                last = (lev == nlev - 1)
                for h in HS:
                    hl = h
                    N_, L_ = NL[h][:, 0:128], NL[h][:, 128:256]
                    k.mm(B[hl][:, 0:128], L_, N_, True, True, [f'NL_{h}'], [bk(hl)])
                    k.mm(B[hl][:, 128:256], N_, L_, True, True, [f'NL_{h}'], [bk(hl)])
                for h in HS:
                    hl = h
                    k.cp('act', NL[h][:], B[hl][:, 0:256], [bk(hl)], [f'NL_{h}'])
                for h in HS:
                    hl = h
                    k.mm(B[hl][:, 256:384], NL[h][:, 128:256], PQ[h][:, 0:128], True, True, [f'NL_{h}', f'PQ_{h}'], [bk(hl)])
                for h in HS:
                    hl = h
                    k.tt('dve', PQ[h][:, 0:128], B[hl][:, 256:384], rd(PQ[h][:, 0:128]), ALU.add, [bk(hl), f'PQ_{h}'], [f'PQ_{h}'])
            yield 'GROUP'
        for h in range(NH):
            k.mm(B[C0][:, h * 64:(h + 1) * 64], A5[h][:, 256:384], vr[:, h * 64:(h + 1) * 64], True, True, [f'A5_{h}', kvr], [bk(C0)])
        yield 'sub'
        k.cp('act', W1[:], B[C0][:, 0:W], [bk(C0)], ['W1'])
        yield 'sub'
        for h in range(NH):
            k.mm(B[C1][:, h * 64:(h + 1) * 64], PQ[h][:, 0:128], W1[:, h * 64:(h + 1) * 64], True, True,
                 [f'PQ_{h}', 'W1'], [bk(C1)])
        yield 'sub'
        k.cp('act', U1[:], B[C1][:, 0:W], [bk(C1)], ['U1'])
        yield 'sub'
        for c in range(NCK):
            cr = slice(c * CH, (c + 1) * CH)
            for h in range(NH):
                k.mm(B[C0][:, h * 64:(h + 1) * 64], FT[h][:, 0, :], ST[h][:], True, True, [f'FT{h}', f'ST{h}'], [bk(C0)])
            yield 'sub'
            k.cp('act', P1s[cr, :], B[C0][cr, 0:W], [bk(C0)], ['P1s'])
            yield 'sub'
            for h in range(NH):
                k.mm(B[C0][:, h * 64:(h + 1) * 64], PQ[h][:, :], P1s[:, h * 64:(h + 1) * 64], True, True,
                     [f'PQ_{h}', 'P1s'], [bk(C0)])
            yield 'sub'
            k.tt('dve', Us[cr, :], B[C0][cr, 0:W], U1[cr, :], ALU.add, [bk(C0), 'U1'], ['Us'])
            yield 'sub'
            for h in range(NH):
                hc_ = slice(h * 64, (h + 1) * 64)
                k.mm(B[C0][:, hc_], FT[h][:, 3, :], ST[h][:], True, False, [f'FT{h}', f'ST{h}'], [bk(C0)])
                k.mm(B[C0][:, hc_], A5[h][:, 384:512], Us[:, hc_], False, False, [f'A5b_{h}', 'Us'], [bk(C0)])
                k.mm(B[C0][:, hc_], A5[h][:, 512:640], vr[:, hc_], False, True, [f'A5b_{h}', kvr], [bk(C0)])
            yield 'sub'
            k.cp('act', ysb[cr, :], B[C0][cr, 0:W], [bk(C0)], ['ysb'])
            for h in range(NH):
                hc_ = slice(h * 64, (h + 1) * 64)
                k.mm(B[C1][0:64, hc_], Bf[c][:, hc_], rdc(Us[:, hc_]), True, False, [kBf, 'Us'], [bk(C1)])
                k.mm(B[C1][0:64, hc_], Kf[c][:, hc_], rd(vr[:, hc_]), False, True, [kKf, kvr], [bk(C1)])
            yield 'sub'
            for h in range(NH):
                hc_ = slice(h * 64, (h + 1) * 64)
                k.stt(ST[h][:], rdc(ST[h][:]), E1T[:, h, (c + 1) * CH - 1:(c + 1) * CH], B[C1][0:64, hc_], ALU.mult, ALU.add,
                      [f'ST{h}', kE1T, bk(C1)], [f'ST{h}'])
        k.P.op('dve', lambda e: e.tensor_reduce(out=m4[:], in_=v3(ysb[:]), axis=AX.X, op=ALU.add), reads=['ysb'], writes=['m4'])
        k.ts('dve', m4[:], m4[:], -1.0 / 64.0, None, ALU.mult, None, ['m4'], ['m4'])
        k.tt('dve', v3(yc[:]), v3(ysb[:]), bc4(m4[:]), ALU.add, ['ysb', 'm4'], ['yc'])
        k.tt('dve', sqp[:], yc[:], yc[:], ALU.mult, ['yc'], ['sqp'])
        k.P.op('dve', lambda e: e.tensor_reduce(out=r4[:], in_=v3(sqp[:]), axis=AX.X, op=ALU.add), reads=['sqp'], writes=['r4'])
        k.ts('dve', r4[:], r4[:], 1.0 / 64.0, GN_EPS, ALU.mult, ALU.add, ['r4'], ['r4'])
        k.act(r4[:], r4[:], AF.Sqrt, ['r4'], ['r4'])
        k.recip(r4[:], r4[:], ['r4'], ['r4'])
        k.tt('dve', v3(yc[:]), v3(yc[:]), bc4(r4[:]), ALU.mult, ['yc', 'r4'], ['yc'])
        k.tt('dve', yc[:], yc[:], lngbc[:], ALU.mult, ['yc', VK[5]], ['yc'])
        k.tt('dve', yc[:], yc[:], lnbbc[:], ALU.add, ['yc', VK[6]], ['yc'])
        k.tt('dve', v3(tmpp[:]), v3(rd(vr[:])), bc4(bon[:]), ALU.mult, [kvr, kbon], ['tmpp'])
        k.tt('dve', yc[:], yc[:], tmpp[:], ALU.add, ['yc', 'tmpp'], ['yc'])
        k.tt('dve', ot[b][:], yc[:], gv[:], ALU.mult, ['yc', kgv], [f'ot{b}'])
        k.dma('pool', oc[rows, :], ot[b][:], r=[f'ot{b}'], final=True)

    gens = {}
    done = set()

    def adv(j):
        try:
            return next(gens[j])
        except StopIteration:
            done.add(j)
            return 'END'

    for step in range(NT + 2):
        jb = step - 2
        if 0 <= jb < NT:
            n_g = 0
            while n_g < 1:
                if adv(jb) == 'GROUP':
                    n_g += 1
        if step < NT:
            gens[step] = tile(step)
            while adv(step) != 'STAGE':
                pass
        ja = step - 1
        a_live = 0 <= ja < NT
        b_live = 0 <= jb < NT
        while a_live or b_live:
            if a_live:
                if adv(ja) == 'STAGE':
                    a_live = False
            if b_live:
                if adv(jb) == 'END':
                    b_live = False
    return k.finish()


def rwkv_consts(CH=64):
    c = -math.exp(-0.5)
    blk = np.kron(np.eye(128 // CH), np.ones((CH, CH)))
    s_idx = np.arange(128)[:, None]
    t_idx = np.arange(128)[None, :]
    triw = np.stack([c * blk * (s_idx <= t_idx), c * blk * (s_idx < t_idx), c * blk * (s_idx > t_idx)]).astype(np.float32)
    lt_, le_, gt_ = blk * (s_idx < t_idx), blk * (s_idx <= t_idx), blk * (t_idx < s_idx)
    mask5 = np.concatenate([lt_, gt_, lt_, le_, le_], 1).astype(np.float32)
    rowm = np.stack([(np.arange(128) < 64), (np.arange(128) >= 64)], 1).astype(np.float32) if CH == 64 else np.ones((128, 2), np.float32)
    return dict(ident=np.eye(128, dtype=np.float32), triw=triw, mask5=mask5, rowm=rowm)


def rwkv_host_inputs(s, p_rwkv, prm, NH=4, CH=64):
    L = p_rwkv.shape[0]
    cs = slice(64 * NH * s, 64 * NH * (s + 1))
    r_, w1, k_, v_, a1, g1 = np.split(p_rwkv, np.cumsum([512, 64, 512, 512, 64])[:5], axis=-1)
    mu = prm['rwkv_mu']
    mur, muw1, muk, muv, mua1, mug1 = np.split(mu, np.cumsum([512, 64, 512, 512, 64])[:5])
    zm = np.zeros(64, np.float32)
    mul = np.concatenate([muw1, zm, mua1, zm, mug1]).reshape(3, 128).T
    vecs = np.stack([prm['rwkv_w0'][cs], prm['rwkv_a0'][cs], prm['rwkv_k_k'][cs], prm['rwkv_k_a'][cs],
                     prm['rwkv_r_k'].reshape(-1)[cs], prm['rwkv_ln_gain'][cs], prm['rwkv_ln_bias'][cs]])
    c_ = np.ascontiguousarray
    d = dict(pr=c_(r_[:, cs]), pk=c_(k_[:, cs]), pv=c_(v_[:, cs]),
             mu1=c_(np.concatenate([mur[cs], muk[cs], muv[cs]])),
             plw=c_(w1.T), pla=c_(a1.T), plg=c_(g1.T), mul=c_(mul),
             w2=c_(prm['rwkv_w2'][:, cs]), a2=c_(prm['rwkv_a2'][:, cs]),
             g2=c_(prm['rwkv_g2'][:, cs]), vecs=c_(vecs))
    d.update(rwkv_consts(CH))
    return d


FM0 = [(0, 128, 0), (128, 128, 128), (256, 128, 256), (384, 128, 384), (1536, 16, 512)] + \
      [(1552 + j * 128, 128, 528 + j * 128) for j in range(4)]
NF0 = 1040
FM1 = [(512, 64, 0), (1600, 64, 64), (1664, 128, 128)] + [(1792 + j * 128, 128, 256 + j * 128) for j in range(8)]
NF1 = 1280


def host_params(inp):
    c_ = lambda a: np.ascontiguousarray(np.asarray(a), dtype=np.float32)
    P = {}
    P['ident'] = np.eye(128, dtype=np.float32)
    P['triu'] = np.triu(np.ones((128, 128), np.float32))
    P['trigt'] = np.tril(np.ones((128, 128), np.float32), -1)
    for l in range(2):
        for j in range(7):
            P[f'g{l}_{j}'] = c_(inp['norm_gain'][l][j])
        for nm in ('xa_wq', 'xa_wk', 'xa_wv', 'xa_wo', 'mlp_w1', 'mlp_w2'):
            P[f'{nm}{l}'] = c_(inp[nm][l])
    P['w_in0'] = c_(inp['ab_w_in'][0])
    P['w_in1'] = c_(inp['cd_w_in'][0])
    P['w_out0'] = c_(inp['ab_w_out'][0])
    P['w_out1'] = c_(inp['cd_w_out'][0])
    P['wglu'] = c_(inp['s5_w_glu'][0])
    P['bglu'] = c_(inp['s5_b_glu'][0])
    prm0 = {k_: np.asarray(inp[k_][0]) for k_ in inp if k_.startswith('s5_') or k_.startswith('gla_')}
    prm1 = {k_: np.asarray(inp[k_][0]) for k_ in inp if k_.startswith('rwkv_') or k_.startswith('lru_')}
    for s in range(2):
        cs = slice(s * 128, (s + 1) * 128)
        P[f'gla_w2_{s}'] = c_(prm0['gla_w_decay2'][:, cs])
        P[f'gla_bd_{s}'] = c_(prm0['gla_b_decay'][None, cs])
        P[f'gla_gn_{s}'] = c_(prm0['gla_norm_gain'][2 * s:2 * s + 2].reshape(256))
        d = s5_host_inputs(s, np.zeros((2, 512), np.float32), prm0)
        for nm in ('lam_re', 'lam_im', 'lstep', 'Bre', 'Bim', 'Cre', 'Cim', 'dsk'):
            P[f's5_{nm}_{s}'] = c_(d[nm])
        P['iota_p'] = c_(d['iota_p'])
        P['iota_f'] = c_(d['iota_f'])
        if s == 0:
            d = rwkv_host_inputs(0, np.zeros((2, 1792), np.float32), prm1, 8, 64)
            for nm in ('mu1', 'mul', 'w2', 'a2', 'g2', 'vecs'):
                P[f'rw_{nm}'] = c_(d[nm])
            for nm in ('triw', 'mask5', 'rowm'):
                P[f'rw_{nm}'] = c_(d[nm])
        d = lru_host_inputs(s, np.zeros((2, 512), np.float32), np.zeros((2, 512), np.float32), prm1)
        for nm in ('cw', 'cb', 'Wa', 'Wx', 'ba', 'bx', 'lam'):
            P[f'lru_{nm}_{s}'] = c_(d[nm])
    return P


def build_fused(P, L):
    k = K(fused=True)
    X = {nm: k.xin(nm, a.shape) for nm, a in P.items()}
    x = k.xin('x', [L, D])
    mem = k.xin('mem', [256, D])
    out = k.xout('out', [L, D])
    proj0 = k.scratch('proj0', [L, 2064])
    PT0 = k.scratch('PT0', [NF0, L])
    proj1 = k.scratch('proj1', [L, 2816])
    PT1 = k.scratch('PT1', [NF1, L])
    o = k.scratch('o', [L, D])
    odT = k.scratch('odT', [512, L])
    h1 = k.scratch('h1', [L, D])
    h2 = k.scratch('h2', [L, D])
    h3 = k.scratch('h3', [L, D])

    def cblock(l, hin, hout, glu, ob_fm):
        io = dict(oa=o[:, 0:512], hin=hin, wout=X[f'w_out{l}'], g1=X[f'g{l}_1'], ident=X['ident'], hout=h1)
        if ob_fm:
            io['obT'] = odT
        else:
            io['ob'] = o[:, 512:1024]
        if glu:
            io.update(wglu=X['wglu'], bglu=X['bglu'])
        k.begin_phase(f'C1_{l}', io)
        build_C1(L, glu, k=k, ob_fm=ob_fm)
        k.begin_phase(f'C2_{l}', dict(hin=h1, mem=mem, wq=X[f'xa_wq{l}'], wk=X[f'xa_wk{l}'], wv=X[f'xa_wv{l}'], wo=X[f'xa_wo{l}'],
                                      g2=X[f'g{l}_2'], g3=X[f'g{l}_3'], g6=X[f'g{l}_6'], ident=X['ident'], hout=h2))
        build_C2(L, k=k)
        k.begin_phase(f'C3_{l}', dict(hin=h2, w1=X[f'mlp_w1{l}'], w2=X[f'mlp_w2{l}'], g4=X[f'g{l}_4'], g5=X[f'g{l}_5'],
                                      ident=X['ident'], hout=hout))
        build_C3(L, k=k)

    k.begin_phase('A0', dict(x=x, gain=X['g0_0'], W=X['w_in0'], ident=X['ident'], out=proj0, outT=PT0))
    build_A2(L, 2064, FM0, NF0, k=k)
    for s in range(2):
        io_g = dict(qT=PT0[s * 128:(s + 1) * 128, :], kT=PT0[256 + s * 128:256 + (s + 1) * 128, :],
                    ktok=proj0[:, 256 + s * 128:256 + (s + 1) * 128], v=proj0[:, 512 + s * 256:512 + (s + 1) * 256],
                    gate=proj0[:, 1024 + s * 256:1024 + (s + 1) * 256], dlrT=PT0[512:528, :],
                    w2=X[f'gla_w2_{s}'], bdec=X[f'gla_bd_{s}'], gn=X[f'gla_gn_{s}'], triu=X['triu'],
                    trigt=X['trigt'], oa=o[:, s * 256:(s + 1) * 256])
        k.begin_phase(f'GLA{s}', io_g)
        build_GLA(L, k=k)
    for s in range(2):
        io_s = dict(uT=PT0[528 + s * 256:528 + (s + 1) * 256, :], u=proj0[:, 1552 + s * 256:1552 + (s + 1) * 256],
                    triu=X['triu'], iota_p=X['iota_p'], iota_f=X['iota_f'], y=o[:, 512 + s * 256:512 + (s + 1) * 256])
        for nm in ('lam_re', 'lam_im', 'lstep', 'Bre', 'Bim', 'Cre', 'Cim', 'dsk'):
            io_s[nm] = X[f's5_{nm}_{s}']
        k.begin_phase(f'S5{s}', io_s)
        build_S5(L, k=k)
    cblock(0, x, h3, True, False)
    k.begin_phase('A1', dict(x=h3, gain=X['g1_0'], W=X['w_in1'], ident=X['ident'], out=proj1, outT=PT1))
    build_A2(L, 2816, FM1, NF1, k=k)
    io = dict(pr=proj1[:, 0:512], pk=proj1[:, 576:1088], pv=proj1[:, 1088:1600], plw=PT1[0:64, :], pla=PT1[64:128, :],
              plg=PT1[128:256, :], ident=X['ident'], triw=X['rw_triw'], mask5=X['rw_mask5'], rowm=X['rw_rowm'], oc=o[:, 0:512])
    for nm in ('mu1', 'mul', 'w2', 'a2', 'g2', 'vecs'):
        io[nm] = X[f'rw_{nm}']
    k.begin_phase('RW', io)
    build_RWKVP(L, k=k, CH=64)
    streams = []
    for s in range(2):
        io = dict(xbT=PT1[256 + s * 256:256 + (s + 1) * 256, :], gateT=PT1[768 + s * 256:768 + (s + 1) * 256, :],
                  odT=odT[s * 256:(s + 1) * 256, :])
        for nm in ('cw', 'cb', 'Wa', 'Wx', 'ba', 'bx', 'lam'):
            io[nm] = X[f'lru_{nm}_{s}']
        streams.append((f'l{s}_', io, lambda kk: gen_LRU(L, kk)))
    k.begin_phase('LRU', {})
    run_streams(k, streams)
    k.finish()
    cblock(1, h3, out, False, True)
    return k.finish_program()


BATCH, SEQ = 4, 4096
_CACHE = {}


def kernel(**inp):
    inp = {k_: np.asarray(v_) for k_, v_ in inp.items()}
    P = host_params(inp)
    if 'nc' not in _CACHE:
        _CACHE['nc'] = build_fused(P, SEQ)
    nc = _CACHE['nc']
    maps = []
    for b in range(BATCH):
        m = dict(P)
        m['x'] = np.ascontiguousarray(inp['x'][b], dtype=np.float32)
        m['mem'] = np.ascontiguousarray(inp['mem'][b], dtype=np.float32)
        maps.append(m)
    res = run_bass_kernel_spmd(nc, maps, core_ids=list(range(BATCH))).results
    return np.ascontiguousarray(np.stack([res[b]['out'] for b in range(BATCH)]).astype(np.float32))
```

```python
import os
import math
from contextlib import ExitStack


import numpy as np
import concourse.bass as bass
import concourse.mybir as mybir
from concourse.bass_utils import run_bass_kernel_spmd

F32 = mybir.dt.float32
BF16 = mybir.dt.bfloat16
I32 = mybir.dt.int32
AF = mybir.ActivationFunctionType
ALU = mybir.AluOpType
AX = mybir.AxisListType

ENGS = ['pe', 'act', 'dve', 'pool', 'sp']
NDMA_SLOTS = 8
SAME_ENGINE_SYNC = os.environ.get("NOSELF", "0") != "1"


class Prog:
    def __init__(self, nc):
        self.nc = nc
        self.ops = {e: [] for e in ENGS}
        self.cnt = {e: 0 for e in ENGS}
        self.last_w = {}
        self.readers = {}
        self.seen = {e: {} for e in ENGS}
        self.dma_n = {e: 0 for e in ENGS}
        self.dma_tok = {e: [None] * NDMA_SLOTS for e in ENGS}
        self.final_tokens = []
        from contextlib import ExitStack
        self.sem_stack = ExitStack()
        self.sems = {}
        for e in ['pe', 'act', 'dve', 'pool']:
            self.sems[('c', e)] = self.sem_stack.enter_context(nc.semaphore("s_c_" + e))
        for q in ['sp', 'pool']:
            for sl in range(NDMA_SLOTS):
                self.sems[('d', q, sl)] = self.sem_stack.enter_context(nc.semaphore(f"s_d_{q}_{sl}"))

    def barrier(self):
        toks = []
        for e in ['pe', 'act', 'dve', 'pool']:
            if self.cnt[e] > 0:
                toks.append((('c', e), self.cnt[e]))
        for q in ENGS:
            for t in self.dma_tok[q]:
                if t is not None:
                    toks.append(t)
        for e in ENGS:
            waits = []
            for (sem, val) in toks:
                if sem == ('c', e):
                    continue
                if self.seen[e].get(sem, 0) >= val:
                    continue
                waits.append((sem, val))
                self.seen[e][sem] = val
            if waits:
                self.ops[e].append((waits, None, None))
        self.last_w = {}
        self.readers = {}

    def _deps(self, eng, reads, writes):
        toks = []
        for r in reads:
            t = self.last_w.get(r)
            if t is not None:
                toks.append(t)
        for w in writes:
            t = self.last_w.get(w)
            if t is not None:
                toks.append(t)
            toks.extend(self.readers.get(w, []))
        need = {}
        for (sem, val) in toks:
            if not SAME_ENGINE_SYNC and sem == ('c', eng):
                continue
            if sem == ('c', 'pe') and eng == 'pe':
                continue
            if self.seen[eng].get(sem, 0) >= val:
                continue
            if need.get(sem, 0) < val:
                need[sem] = val
        for sem, val in need.items():
            self.seen[eng][sem] = val
        return list(need.items())

    def _commit(self, tok, reads, writes):
        for w in writes:
            self.last_w[w] = tok
            self.readers[w] = []
        for r in reads:
            if r in writes:
                continue
            self.readers.setdefault(r, []).append(tok)

    def op(self, eng, fn, reads=(), writes=()):
        self.nrec = getattr(self, 'nrec', 0) + 1
        if self.nrec > int(os.environ.get("MAXOPS", "100000000")):
            return None
        kp = getattr(self, 'key_prefix', '')
        reads = [r if r.startswith('ps') else kp + r for r in reads]
        writes = [w if w.startswith('ps') else kp + w for w in writes]
        pk = getattr(self, 'ps_prefix', '')
        reads = [('ps' + pk + r[2:]) if r.startswith('ps') else r for r in reads]
        writes = [('ps' + pk + w[2:]) if w.startswith('ps') else w for w in writes]
        writes = list(writes) + [r for r in reads if r.startswith('ps') and r not in writes]
        waits = self._deps(eng, reads, writes)
        self.cnt[eng] += 1
        tok = (('c', eng), self.cnt[eng])
        self.ops[eng].append((waits, fn, tok))
        self._commit(tok, reads, writes)
        return tok

    def dma(self, q, out, in_, reads=(), writes=(), final=False, **kw):
        self.nrec = getattr(self, 'nrec', 0) + 1
        if self.nrec > int(os.environ.get("MAXOPS", "100000000")):
            return None
        kp = getattr(self, 'key_prefix', '')
        reads = [kp + r for r in reads]
        writes = [kp + w for w in writes]
        waits = self._deps(q, reads, writes)
        n = self.dma_n[q]
        slot = n % NDMA_SLOTS
        prev = self.dma_tok[q][slot]
        if prev is not None and self.seen[q].get(prev[0], 0) < prev[1]:
            waits.append(prev)
            self.seen[q][prev[0]] = prev[1]
        tok = (('d', q, slot), 16 * (n // NDMA_SLOTS + 1))
        self.dma_n[q] += 1
        self.dma_tok[q][slot] = tok

        def fn(e, out=out, in_=in_, kw=kw):
            return e.dma_start(out=out, in_=in_, **kw)
        self.ops[q].append((waits, fn, tok))
        self._commit(tok, reads, writes)
        if final:
            self.final_tokens.append(tok)
        return tok

    def emit(self, last=True):
        nc = self.nc
        sems = self.sems
        with nc.Block() as block:
            final = list(self.final_tokens) if last else []

            def run(e, name):
                for waits, fn, tok in self.ops[name]:
                    for (s, v) in waits:
                        e.wait_ge(sems[s], v)
                    if fn is None:
                        continue
                    inst = fn(e)
                    inc = 16 if tok[0][0] == 'd' else 1
                    inst.then_inc(sems[tok[0]], inc)
                if name == 'sp':
                    for (s, v) in final:
                        e.wait_ge(sems[s], v)
                self.ops[name] = []

            @block.tensor
            def _(e):
                run(e, 'pe')

            @block.scalar
            def _(e):
                run(e, 'act')

            @block.vector
            def _(e):
                run(e, 'dve')

            @block.gpsimd
            def _(e):
                run(e, 'pool')

            @block.sync
            def _(e):
                run(e, 'sp')
        if last:
            self.sem_stack.close()


D = 1024
KC = 8
EPS = 1e-6


class K:
    def __init__(self, fused=False):
        self.nc = bass.Bass("TRN2", target_bir_lowering=False)
        self.st = ExitStack()
        self.P = Prog(self.nc)
        self.n = 0
        self.fused = fused
        self.io = {}
        self.pfx = ""

    def begin_phase(self, name, io):
        self.pfx = name + "_"
        self.io = io
        self.st = ExitStack()
        for a in ('wstage', 'rr_cache', 'identf', 'identb'):
            if hasattr(self, a):
                delattr(self, a)

    def scratch(self, name, shape, dt=F32):
        return self.nc.dram_tensor(name, list(shape), dt, kind="Internal").ap()

    def xin(self, name, arr_shape, dt=F32):
        return self.nc.dram_tensor(name, list(arr_shape), dt, kind="ExternalInput").ap()

    def xout(self, name, arr_shape, dt=F32):
        return self.nc.dram_tensor(name, list(arr_shape), dt, kind="ExternalOutput").ap()

    def din(self, name, shape, dt=F32):
        if self.fused:
            ap = self.io[name]
            assert list(ap.shape) == list(shape), (name, ap.shape, shape)
            return ap
        return self.nc.dram_tensor(name, list(shape), dt, kind="ExternalInput").ap()

    def dout(self, name, shape, dt=F32):
        if self.fused:
            ap = self.io[name]
            assert list(ap.shape) == list(shape), (name, ap.shape, shape)
            return ap
        return self.nc.dram_tensor(name, list(shape), dt, kind="ExternalOutput").ap()

    def sb(self, name, shape, dt=F32):
        pers = getattr(self, 'persist', None)
        if pers is not None and (self.pfx + name) in pers:
            return pers[self.pfx + name]
        return self.st.enter_context(self.nc.sbuf_tensor(self.pfx + name, list(shape), dt))

    def push_scope(self, persistent):
        self.persist = getattr(self, 'persist', None) or {}
        for (name, shape, dt) in persistent:
            self.persist[self.pfx + name] = self.st.enter_context(self.nc.sbuf_tensor(self.pfx + name, list(shape), dt))
        self._st_saved = self.st
        self.st = ExitStack()

    def pop_scope(self):
        self.P.barrier()
        self.P.emit(last=False)
        self.st.close()
        self.st = self._st_saved

    def ps(self, name, shape, dt=F32):
        return self.st.enter_context(self.nc.psum_tensor(self.pfx + name, list(shape), dt))

    def finish(self, last=True):
        if self.fused:
            self.P.barrier()
            self.P.emit(last=False)
            self.st.close()
            return None
        self.P.emit()
        self.st.close()
        return self.nc

    def finish_program(self):
        self.P.emit(last=True)
        return self.nc

    def mm(self, out, lhsT, rhs, start, stop, r, w):
        self.P.op('pe', lambda e: e.matmul(out, lhsT=lhsT, rhs=rhs, start=start, stop=stop), reads=r, writes=w)

    def tr(self, out, in_, ident, r, w):
        self.P.op('pe', lambda e: e.transpose(out=out, in_=in_, identity=ident), reads=list(r) + ['ident'], writes=w)

    def act(self, out, in_, func, r, w, **kw):
        self.P.op('act', lambda e: e.activation(out=out, in_=in_, func=func, **kw), reads=r, writes=w)

    def tt(self, eng, out, in0, in1, op, r, w):
        self.P.op(eng, lambda e: e.tensor_tensor(out=out, in0=in0, in1=in1, op=op), reads=r, writes=w)

    def ts(self, eng, out, in0, s1, s2, op0, op1, r, w):
        if op1 is None:
            self.P.op(eng, lambda e: e.tensor_scalar(out=out, in0=in0, scalar1=s1, scalar2=None, op0=op0), reads=r, writes=w)
        else:
            self.P.op(eng, lambda e: e.tensor_scalar(out=out, in0=in0, scalar1=s1, scalar2=s2, op0=op0, op1=op1), reads=r, writes=w)

    def stt(self, out, in0, scalar, in1, op0, op1, r, w):
        self.P.op('dve', lambda e: e.scalar_tensor_tensor(out=out, in0=in0, scalar=scalar, in1=in1, op0=op0, op1=op1),
                  reads=r, writes=w)

    def cp(self, eng, out, in_, r, w):
        if eng == 'act':
            self.P.op('act', lambda e: e.copy(out=out, in_=in_), reads=r, writes=w)
        else:
            self.P.op(eng, lambda e: e.tensor_copy(out=out, in_=in_), reads=r, writes=w)

    def recip(self, out, in_, r, w):
        self.P.op('dve', lambda e: e.reciprocal(out=out, in_=in_), reads=r, writes=w)

    def memset(self, eng, ap, val, w):
        self.P.op(eng, lambda e: e.memset(ap, val), reads=[], writes=w)

    def dma(self, q, out, in_, r=(), w=(), final=False, **kw):
        self.P.dma(q, out, in_, reads=r, writes=w, final=final, **kw)

    def consts(self, ident_d):
        self.identf = self.sb("identf", [128, 128], F32)
        self.identb = self.sb("identb", [128, 128], BF16)
        self.dma('sp', self.identf[:], ident_d, w=['ident'])
        self.cp('dve', self.identb[:], self.identf[:], ['ident'], ['ident'])

    def gain_cols(self, name, g_d):
        t = self.sb(name, [128, KC], F32)
        self.dma('sp', t[:], g_d.rearrange("(kc p) -> p kc", p=128), w=[name], allow_slow_non_contiguous=True)
        return t

    def bcast_row(self, name, vec_d, n):
        t = self.sb(name, [128, n], F32)
        self.dma('sp', t[:], vec_d.partition_broadcast(128), w=[name])
        return t

    def load_weight(self, name, w_d, kchunks, ncols, gcol=None, gkey=None, stage_cols=2048, q='sp', chunk_keys=False):
        wb = self.sb(name, [128, kchunks, ncols], BF16)
        if not hasattr(self, 'wstage'):
            self.wstage = [self.sb(f"wstage{i}", [128, stage_cols], F32) for i in range(2)]
            self.wstage_n = 0
            self.wstage_cols = stage_cols
        sc = self.wstage_cols
        wv = w_d.rearrange("(kc p) n -> p kc n", p=128)
        order = [(kc, c0) for kc in range(kchunks) for c0 in range(0, ncols, sc)]
        if chunk_keys:
            order = [(kc, c0) for c0 in range(0, ncols, sc) for kc in range(kchunks)]
        for (kc, c0) in order:
            if True:
                cw = min(sc, ncols - c0)
                wkey = f'{name}{kc}_{c0 // sc}' if chunk_keys else f'{name}{kc}'
                b = self.wstage_n % 2
                self.wstage_n += 1
                stg = self.wstage[b]
                self.dma(q, stg[:, 0:cw], wv[:, kc, c0:c0 + cw], w=[f'wstage{b}'])
                eng = 'act' if (kc % 2 == 0) else 'dve'
                if gcol is not None:
                    if eng == 'act':
                        self.act(wb[:, kc, c0:c0 + cw], stg[:, 0:cw], AF.Copy, [f'wstage{b}', gkey], [wkey],
                                 scale=gcol[:, kc:kc + 1])
                    else:
                        self.ts('dve', wb[:, kc, c0:c0 + cw], stg[:, 0:cw], gcol[:, kc:kc + 1], None, ALU.mult, None,
                                [f'wstage{b}', gkey], [wkey])
                else:
                    self.cp(eng, wb[:, kc, c0:c0 + cw], stg[:, 0:cw], [f'wstage{b}'], [wkey])
        return wb

    def rstd_of(self, x_ap, xkey, ss, rstd, junk, key, ncols=D):
        self.act(junk, x_ap, AF.Square, [xkey], ['junk', key + 'ss'], accum_out=ss)
        self.ts('dve', rstd, ss, 1.0 / ncols, EPS, ALU.mult, ALU.add, [key + 'ss'], [key])
        self.act(rstd, rstd, AF.Sqrt, [key], [key])
        self.recip(rstd, rstd, [key], [key])


def pipeline(make_gen, n):
    active = []
    for i in range(n):
        for g in list(active):
            try:
                next(g)
            except StopIteration:
                active.remove(g)
        g = make_gen(i)
        active.append(g)
        try:
            next(g)
        except StopIteration:
            active.remove(g)
    while active:
        for g in list(active):
            try:
                next(g)
            except StopIteration:
                active.remove(g)


def pipeline_gen(make_gen, n):
    active = []
    for i in range(n):
        for g in list(active):
            try:
                next(g)
            except StopIteration:
                active.remove(g)
        g = make_gen(i)
        active.append(g)
        try:
            next(g)
        except StopIteration:
            active.remove(g)
        yield
    while active:
        for g in list(active):
            try:
                next(g)
            except StopIteration:
                active.remove(g)
        yield


def run_streams(k, streams):
    base_pfx = k.pfx
    gens = []
    for (pf, io, gf) in streams:
        gens.append([pf, io, None, gf])
    active = list(gens)
    while active:
        for st in list(active):
            pf, io, g, gf = st
            k.pfx = base_pfx + pf
            k.P.key_prefix = pf
            k.P.ps_prefix = pf
            k.io = io
            try:
                if g is None:
                    st[2] = gf(k)
                    g = st[2]
                next(g)
            except StopIteration:
                active.remove(st)
    k.pfx = base_pfx
    k.P.key_prefix = ''
    k.P.ps_prefix = ''


GELU_C = 1.5957691216057308


def norm_T(k, xt, xkey, xn, xnkey, xT_dst, xTkey, psT, psTkey, ss, rstd, junk, key, evac_eng='act'):
    k.rstd_of(xt, xkey, ss, rstd, junk, key)
    k.ts('dve', xn, xt, rstd, None, ALU.mult, None, [xkey, key], [xnkey])
    for kc in range(KC):
        k.tr(psT[:, kc * 128:(kc + 1) * 128], xn[:, kc * 128:(kc + 1) * 128], k.identb[:], [xnkey], [psTkey])
    k.cp(evac_eng, xT_dst, psT[:].rearrange("p (k t) -> p k t", k=KC), [psTkey], [xTkey])


def post_norm_res(k, ps2, pskeys, ht, hkey, gbc, gkey, tmp2, tmpkeys, ss2, rstd, junk, key):
    for j in range(2):
        k.act(junk[:, 0:512], ps2[j], AF.Square, [pskeys[j]], ['junk', key + f'ss{j}'], accum_out=ss2[:, j:j + 1])
    k.tt('dve', ss2[:, 0:1], ss2[:, 0:1], ss2[:, 1:2], ALU.add, [key + 'ss0', key + 'ss1'], [key + 'ss0'])
    k.ts('dve', rstd, ss2[:, 0:1], 1.0 / D, EPS, ALU.mult, ALU.add, [key + 'ss0'], [key])
    k.act(rstd, rstd, AF.Sqrt, [key], [key])
    k.recip(rstd, rstd, [key], [key])
    for j in range(2):
        sl = slice(j * 512, (j + 1) * 512)
        k.stt(tmp2[j], ps2[j], rstd, gbc[:, sl], ALU.mult, ALU.mult, [pskeys[j], key, gkey], [tmpkeys[j]])
        k.tt('pool', ht[:, sl], ht[:, sl], tmp2[j], ALU.add, [tmpkeys[j], hkey], [hkey])


def build_C1(NTOK, glu, k=None, ob_fm=False):
    k = k or K()
    NT = NTOK // 128
    oa = k.din("oa", [NTOK, 512])
    if ob_fm:
        obT = k.din("obT", [512, NTOK])
    else:
        ob = k.din("ob", [NTOK, 512])
    hin = k.din("hin", [NTOK, D])
    wout = k.din("wout", [D, D])
    g1 = k.din("g1", [D])
    ident_d = k.din("ident", [128, 128])
    if glu:
        wglu = k.din("wglu", [512, 512])
        bglu = k.din("bglu", [512])
    hout = k.dout("hout", [NTOK, D])
    k.consts(ident_d)
    g1bc = k.bcast_row("g1bc", g1, D)
    Wout = k.load_weight("Wout", wout, KC, D, stage_cols=1024)
    if glu:
        Wglu = k.load_weight("Wglu", wglu, 4, 512)
        bgbc = k.bcast_row("bgbc", bglu, 512)

    def ring(nm, shape, n, dt=F32):
        return [k.sb(f"{nm}{j}", shape, dt) for j in range(n)]
    oc = ring("oc", [128, D], 10 if glu else 4)
    ocb = ring("ocb", [128, D], 3, BF16)
    oT = ring("oT", [128, KC, 128], 3, BF16)
    ht = ring("ht", [128, D], 4)
    mix = ring("mix", [128, D], 5)
    tmp = ring("tmp", [128, D], 3)
    ss2 = ring("ss2", [128, 2], 4)
    rstd = ring("rstd", [128, 1], 5)
    junk = k.sb("junk", [128, D], BF16)
    if ob_fm:
        obt = ring("obt", [128, 4, 128], 4)
    if glu:
        yb = ring("yb", [128, 512], 3, BF16)
        yT = ring("yT", [128, 4, 128], 3, BF16)
        t1 = ring("t1", [128, 512], 9)
        zs = ring("zs", [128, 512], 4)
        psTg = k.ps("psTg", [128, D], BF16)
        psG = k.ps("psG", [128, 512])
    psTm = [k.ps(f"psTm{j}", [128, D], BF16) for j in range(2)]
    psM = [k.ps(f"psM{j}", [128, 512]) for j in range(4)]

    def tile(i):
        rows = slice(i * 128, (i + 1) * 128)
        def T(lst, nm):
            j = i % len(lst)
            return lst[j], f'{nm}{j}'
        oc_, koc = T(oc, 'oc'); ocb_, kocb = T(ocb, 'ocb'); oT_, koT = T(oT, 'oT'); ht_, kht = T(ht, 'ht')
        mix_, kmix = T(mix, 'mix'); tmp_, ktmp = T(tmp, 'tmp'); ss_, kss = T(ss2, 'ss2'); rs_, krs = T(rstd, 'rstd')
        pm = [psM[2 * (i % 2)], psM[2 * (i % 2) + 1]]
        kpm = [f'psM{2 * (i % 2)}', f'psM{2 * (i % 2) + 1}']
        ptm, kptm = psTm[i % 2], f'psTm{i % 2}'
        kA, kB = koc + 'A', koc + 'B'
        k.dma('sp', oc_[:, 0:512], oa[rows, :], w=[kA])
        if ob_fm:
            obt_, kobt = T(obt, 'obt')
            k.dma('sp', obt_[:], obT[:, rows].rearrange("(a p) t -> p a t", p=128), w=[kobt])
        else:
            k.dma('sp', oc_[:, 512:1024], ob[rows, :], w=[kB])
        yield
        if glu:
            y = oc_[:, 512:1024]
            yb_, kyb = T(yb, 'yb'); yT_, kyT = T(yT, 'yT'); t1_, kt1 = T(t1, 't1'); zs_, kzs = T(zs, 'zs')
            k.cp('dve', yb_[:], y, [kB], [kyb])
            k.act(t1_[:], y, AF.Square, [kB], [kt1])
            k.act(t1_[:], t1_[:], AF.Copy, [kt1], [kt1], scale=0.044715, bias=1.0)
            yield
            for kc in range(4):
                k.tr(psTg[:, kc * 128:(kc + 1) * 128], yb_[:, kc * 128:(kc + 1) * 128], k.identb[:], [kyb], ['psTg'])
            k.tt('pool', t1_[:], t1_[:], y, ALU.mult, [kt1, kB], [kt1])
            yield
            k.cp('act', yT_[:], psTg[:, 0:512].rearrange("p (k t) -> p k t", k=4), ['psTg'], [kyT])
            k.act(t1_[:], t1_[:], AF.Sigmoid, [kt1], [kt1], scale=GELU_C)
            yield
            for kc in range(4):
                k.mm(psG[:], yT_[:, kc, :], Wglu[:, kc, :], kc == 0, kc == 3, [kyT, f'Wglu{kc}'], ['psG'])
            yield
            k.tt('dve', zs_[:], psG[:], bgbc[:], ALU.add, ['psG', 'bgbc'], [kzs])
            yield
            k.act(zs_[:], zs_[:], AF.Sigmoid, [kzs], [kzs])
            yield
            k.tt('dve', zs_[:], t1_[:], zs_[:], ALU.mult, [kt1, kzs], [kzs])
            k.tt('dve', y, y, zs_[:], ALU.mult, [kB, kzs], [kB])
        if ob_fm:
            k.cp('dve', ocb_[:, 0:512], oc_[:, 0:512], [kA], [kocb])
            k.cp('pool', oT_[:, 4:8, :], obt_[:], [kobt], [koT + 'b'])
        else:
            k.cp('dve', ocb_[:], oc_[:], [kA, kB], [kocb])
        yield
        nk = 4 if ob_fm else KC
        for kc in range(nk):
            k.tr(ptm[:, kc * 128:(kc + 1) * 128], ocb_[:, kc * 128:(kc + 1) * 128], k.identb[:], [kocb], [kptm])
        yield
        k.cp('act', oT_[:, 0:nk, :], ptm[:, 0:nk * 128].rearrange("p (k t) -> p k t", k=nk), [kptm], [koT])
        yield
        for cg in range(2):
            for kc in range(KC):
                ok_ = (koT + 'b') if (ob_fm and kc >= 4) else koT
                k.mm(pm[cg][:], oT_[:, kc, :], Wout[:, kc, cg * 512:(cg + 1) * 512], kc == 0, kc == KC - 1,
                     [ok_, f'Wout{kc}'], [kpm[cg]])
        yield
        for j in range(2):
            k.act(junk[:, 0:512], pm[j][:], AF.Square, [kpm[j]], ['junk', kss], accum_out=ss_[:, j:j + 1])
        for j in range(2):
            k.cp('act', mix_[:, j * 512:(j + 1) * 512], pm[j][:], [kpm[j]], [kmix])
        k.dma('sp', ht_[:], hin[rows, :], w=[kht])
        yield
        k.tt('dve', ss_[:, 0:1], ss_[:, 0:1], ss_[:, 1:2], ALU.add, [kss], [kss])
        k.ts('dve', rs_[:], ss_[:, 0:1], 1.0 / D, EPS, ALU.mult, ALU.add, [kss], [krs])
        yield
        k.act(rs_[:], rs_[:], AF.Sqrt, [krs], [krs])
        yield
        k.recip(rs_[:], rs_[:], [krs], [krs])
        k.stt(tmp_[:], mix_[:], rs_[:], g1bc[:], ALU.mult, ALU.mult, [kmix, krs, 'g1bc'], [ktmp])
        yield
        k.tt('pool', ht_[:], ht_[:], tmp_[:], ALU.add, [kht, ktmp], [kht])
        k.dma('pool', hout[rows, :], ht_[:], r=[kht], final=True)

    pipeline(tile, NT)
    return k.finish()


def build_C3(NTOK, k=None):
    k = k or K()
    NB = NTOK // 512
    DFF = 4096
    FC = DFF // 128
    hin = k.din("hin", [NTOK, D])
    w1 = k.din("w1", [D, DFF])
    w2 = k.din("w2", [DFF, D])
    g4 = k.din("g4", [D])
    g5 = k.din("g5", [D])
    ident_d = k.din("ident", [128, 128])
    hout = k.dout("hout", [NTOK, D])
    k.consts(ident_d)
    g4c = k.gain_cols("g4c", g4)
    g5bc = k.bcast_row("g5bc", g5, D)
    W1 = k.load_weight("W1", w1, KC, DFF, gcol=g4c, gkey='g4c', stage_cols=512)
    W2 = k.load_weight("W2", w2, FC, D, stage_cols=512)
    ht = [k.sb(f"ht{i}", [128, D]) for i in range(4)]
    xn = [k.sb(f"xn{i}", [128, D], BF16) for i in range(2)]
    xT = k.sb("xT", [128, KC, 512], BF16)
    AT = k.sb("AT", [128, FC, 512], BF16)
    sq = [k.sb(f"sq{i}", [128, 512]) for i in range(2)]
    junk = k.sb("junk", [128, D], BF16)
    ss = [k.sb(f"ss{i}", [128, 1]) for i in range(2)]
    ss2 = [k.sb(f"ss2{i}", [128, 2]) for i in range(2)]
    rstd = [k.sb(f"rstd{i}", [128, 1]) for i in range(2)]
    rstd2 = [k.sb(f"rstdb{i}", [128, 1]) for i in range(2)]
    ss4 = k.sb("ss4", [128, 4])
    rs4 = k.sb("rs4", [128, 4])
    psT = k.ps("psT", [128, D], BF16)
    psU = [k.ps(f"psU{i}", [128, 512]) for i in range(3)]
    psD = [k.ps(f"psD{i}", [128, 512]) for i in range(4)]
    nu = 0
    for blk in range(NB):
        for tt in range(4):
            i = blk * 4 + tt
            k.dma('sp', ht[tt][:], hin[i * 128:(i + 1) * 128, :], w=[f'ht{tt}'])
        for tt in range(4):
            k.act(junk[:], ht[tt][:], AF.Square, [f'ht{tt}'], ['junk', f'nss{tt}'], accum_out=ss4[:, tt:tt + 1])
        k.ts('dve', rs4[:], ss4[:], 1.0 / D, EPS, ALU.mult, ALU.add, [f'nss{t_}' for t_ in range(4)], ['rs4'])
        k.act(rs4[:], rs4[:], AF.Sqrt, ['rs4'], ['rs4'])
        k.recip(rs4[:], rs4[:], ['rs4'], ['rs4'])
        for tt in range(4):
            b = tt % 2
            k.ts('dve', xn[b][:], ht[tt][:], rs4[:, tt:tt + 1], None, ALU.mult, None, [f'ht{tt}', 'rs4'], [f'xn{b}'])
            for kc in range(KC):
                k.tr(psT[:, kc * 128:(kc + 1) * 128], xn[b][:, kc * 128:(kc + 1) * 128], k.identb[:], [f'xn{b}'], ['psT'])
            k.cp('act', xT[:, :, tt * 128:(tt + 1) * 128], psT[:].rearrange("p (k t) -> p k t", k=KC), ['psT'], ['xT'])
        for fc in range(FC):
            pu = nu % 3
            nu += 1
            for kc in range(KC):
                k.mm(psU[pu][:], W1[:, kc, fc * 128:(fc + 1) * 128], xT[:, kc, :], kc == 0, kc == KC - 1,
                     [f'W1{kc}', 'xT'], [f'psU{pu}'])
            sb_ = fc % 2
            k.act(sq[sb_][:], psU[pu][:], AF.Square, [f'psU{pu}'], [f'sq{sb_}'])
            k.stt(AT[:, fc, :], psU[pu][:], 0.0, sq[sb_][:], ALU.is_gt, ALU.mult, [f'psU{pu}', f'sq{sb_}'], ['AT'])
        for tt in range(4):
            i = blk * 4 + tt
            b = i % 2
            rows = slice(i * 128, (i + 1) * 128)
            for cg in range(2):
                pd = 2 * b + cg
                for fc in range(FC):
                    k.mm(psD[pd][:], AT[:, fc, tt * 128:(tt + 1) * 128], W2[:, fc, cg * 512:(cg + 1) * 512],
                         fc == 0, fc == FC - 1, ['AT', f'W2{fc}'], [f'psD{pd}'])
            post_norm_res(k, [psD[2 * b][:], psD[2 * b + 1][:]], [f'psD{2 * b}', f'psD{2 * b + 1}'], ht[tt], f'ht{tt}',
                          g5bc, 'g5bc', [sq[0][:], sq[1][:]], ['sq0', 'sq1'], ss2[b], rstd2[b][:], junk, f'pn{b}')
            k.dma('pool', hout[rows, :], ht[tt][:], r=[f'ht{tt}'], final=True)
    return k.finish()


def build_C2(NTOK, k=None):
    k = k or K()
    NB = NTOK // 512
    MEM = 256
    hin = k.din("hin", [NTOK, D])
    mem = k.din("mem", [MEM, D])
    wq = k.din("wq", [D, D])
    wk = k.din("wk", [D, D])
    wv = k.din("wv", [D, D])
    wo = k.din("wo", [D, D])
    g2 = k.din("g2", [D])
    g3 = k.din("g3", [D])
    g6 = k.din("g6", [D])
    ident_d = k.din("ident", [128, 128])
    hout = k.dout("hout", [NTOK, D])
    k.consts(ident_d)
    g2c = k.gain_cols("g2c", g2)
    g6c = k.gain_cols("g6c", g6)
    g3bc = k.bcast_row("g3bc", g3, D)
    Wk = k.load_weight("Wk", wk, KC, D, gcol=g6c, gkey='g6c', stage_cols=1024)
    Wv = k.load_weight("Wv", wv, KC, D, gcol=g6c, gkey='g6c', stage_cols=1024)
    Wq = k.load_weight("Wq", wq, KC, D, gcol=g2c, gkey='g2c', stage_cols=1024)
    Wo = k.load_weight("Wo", wo, KC, D, stage_cols=1024)
    ht = [k.sb(f"ht{i}", [128, D]) for i in range(2)]
    xn = [k.sb(f"xn{i}", [128, D], BF16) for i in range(2)]
    xT = [k.sb(f"xT{i}", [128, KC, 512], BF16) for i in range(2)]
    memT = k.sb("memT", [128, KC, MEM], BF16)
    KT = k.sb("KT", [128, KC, MEM], BF16)
    V = k.sb("V", [128, 2, D], BF16)
    QT = [k.sb(f"QT{i}", [128, KC, 512], BF16) for i in range(2)]
    Pm = [k.sb(f"Pm{i}", [128, 4, MEM], BF16) for i in range(3)]
    Pn = [k.sb(f"Pn{i}", [128, 4, MEM], BF16) for i in range(3)]
    PT = [k.sb(f"PT{i}", [128, 8, 128], BF16) for i in range(3)]
    OT = [k.sb(f"OT{i}", [128, KC, 128], BF16) for i in range(3)]
    tmp = [k.sb(f"tmp{i}", [128, 512]) for i in range(2)]
    junk = k.sb("junk", [128, D], BF16)
    ss = [k.sb(f"ss{i}", [128, 1]) for i in range(2)]
    ss2 = [k.sb(f"ss2{i}", [128, 2]) for i in range(2)]
    rstd = [k.sb(f"rstd{i}", [128, 1]) for i in range(2)]
    rstd2 = [k.sb(f"rstdb{i}", [128, 1]) for i in range(2)]
    mx = [k.sb(f"mx{i}", [128, 4]) for i in range(3)]
    sm = [k.sb(f"sm{i}", [128, 4]) for i in range(3)]
    psT = k.ps("psT", [128, D], BF16)
    psA = k.ps("psA", [128, 1024])
    psS = k.ps("psS", [128, 1024])
    psX = k.ps("psX", [128, 1024])
    for mt in range(2):
        k.dma('sp', ht[mt][:], mem[mt * 128:(mt + 1) * 128, :], w=[f'ht{mt}'])
        norm_T(k, ht[mt][:], f'ht{mt}', xn[mt][:], f'xn{mt}', memT[:, :, mt * 128:(mt + 1) * 128], 'memT', psT[:], 'psT',
               ss[mt][:], rstd[mt][:], junk[:], f'n{mt}')
    for cc in range(KC):
        pa = cc % 2
        for kc in range(KC):
            k.mm(psA[:, pa * 512:pa * 512 + MEM], Wk[:, kc, cc * 128:(cc + 1) * 128], memT[:, kc, :], kc == 0, kc == KC - 1,
                 [f'Wk{kc}', 'memT'], [f'psA{pa}'])
        k.cp('act' if cc % 2 else 'dve', KT[:, cc, :], psA[:, pa * 512:pa * 512 + MEM], [f'psA{pa}'], [f'KT{cc}'])
    for mt in range(2):
        for cg in range(2):
            for kc in range(KC):
                k.mm(psX[:, cg * 512:(cg + 1) * 512], memT[:, kc, mt * 128:(mt + 1) * 128], Wv[:, kc, cg * 512:(cg + 1) * 512],
                     kc == 0, kc == KC - 1, ['memT', f'Wv{kc}'], [f'psX{cg}'])
            k.cp('act' if cg else 'dve', V[:, mt, cg * 512:(cg + 1) * 512], psX[:, cg * 512:(cg + 1) * 512], [f'psX{cg}'], [f'V{mt}{cg}'])
    xt6 = [k.sb(f"xt6_{i}", [128, D]) for i in range(6)]
    ss6 = [k.sb(f"ss6_{i}", [128, 1]) for i in range(4)]
    rs6 = [k.sb(f"rs6_{i}", [128, 1]) for i in range(5)]
    xn3 = [k.sb(f"xn3_{i}", [128, D], BF16) for i in range(3)]
    psTx = k.ps("psTx", [128, D], BF16)

    def tile(i):
        blk, tt = divmod(i, 4)
        xb = blk % 2
        b = i % 3
        rows = slice(i * 128, (i + 1) * 128)
        tsl = slice(tt * 128, (tt + 1) * 128)
        def T(lst, nm):
            j = i % len(lst)
            return lst[j], f'{nm}{j}'
        xt_, kxt = T(xt6, 'xt6'); ss_, kss = T(ss6, 'ss6'); rs_, krs = T(rs6, 'rs6'); xn_, kxn = T(xn3, 'xn3')
        hb = i % 2
        k.dma('sp', xt_[:], hin[rows, :], w=[kxt])
        yield
        k.act(junk[:], xt_[:], AF.Square, [kxt], ['junk', kss], accum_out=ss_[:])
        yield
        k.ts('dve', rs_[:], ss_[:], 1.0 / D, EPS, ALU.mult, ALU.add, [kss], [krs])
        yield
        k.act(rs_[:], rs_[:], AF.Sqrt, [krs], [krs])
        yield
        k.recip(rs_[:], rs_[:], [krs], [krs])
        k.ts('dve', xn_[:], xt_[:], rs_[:], None, ALU.mult, None, [kxt, krs], [kxn])
        yield
        for kc in range(KC):
            k.tr(psTx[:, kc * 128:(kc + 1) * 128], xn_[:, kc * 128:(kc + 1) * 128], k.identb[:], [kxn], ['psTx'])
        yield
        k.cp('act', xT[xb][:, :, tsl], psTx[:].rearrange("p (k t) -> p k t", k=KC), ['psTx'], [f'xT{xb}'])
        yield
        if tt == 3:
            for cc in range(KC):
                pa = cc % 2
                for kc in range(KC):
                    k.mm(psA[:, pa * 512:(pa + 1) * 512], Wq[:, kc, cc * 128:(cc + 1) * 128], xT[xb][:, kc, :], kc == 0, kc == KC - 1,
                         [f'Wq{kc}', f'xT{xb}'], [f'psA{pa}'])
                k.cp('act' if cc % 2 else 'dve', QT[xb][:, cc, :], psA[:, pa * 512:(pa + 1) * 512], [f'psA{pa}'], [f'QT{xb}{cc}'])
        yield
        yield
        yield
        yield
        for h in range(4):
            sb_ = h // 2
            for j in range(2):
                cc = 2 * h + j
                k.mm(psS[:, h * MEM:(h + 1) * MEM], QT[xb][:, cc, tsl], KT[:, cc, :], j == 0, j == 1,
                     [f'QT{xb}{cc}', f'KT{cc}'], [f'psS{sb_}'])
        k.P.op('dve', lambda e, b=b: e.tensor_reduce(out=mx[b][:], in_=psS[:].rearrange("p (h m) -> p h m", h=4),
                                                    axis=AX.X, op=ALU.max),
               reads=['psS0', 'psS1'], writes=[f'mx{b}'])
        k.ts('dve', mx[b][:], mx[b][:], -1.0 / 16.0, None, ALU.mult, None, [f'mx{b}'], [f'mx{b}'])
        for h in range(4):
            k.act(Pm[b][:, h, :], psS[:, h * MEM:(h + 1) * MEM], AF.Exp, [f'psS{h // 2}', f'mx{b}'], [f'Pm{b}', f'sm{b}'],
                  scale=1.0 / 16.0, bias=mx[b][:, h:h + 1], accum_out=sm[b][:, h:h + 1])
        k.recip(sm[b][:], sm[b][:], [f'sm{b}'], [f'sm{b}'])
        k.tt('dve', Pn[b][:], Pm[b][:], sm[b][:].unsqueeze(2).broadcast_to([128, 4, MEM]), ALU.mult,
             [f'Pm{b}', f'sm{b}'], [f'Pn{b}'])
        yield
        for h in range(4):
            for mt in range(2):
                k.tr(psT[:, (h * 2 + mt) * 128:(h * 2 + mt + 1) * 128], Pn[b][:, h, mt * 128:(mt + 1) * 128], k.identb[:],
                     [f'Pn{b}'], ['psT'])
        k.cp('act', PT[b][:], psT[:].rearrange("p (k t) -> p k t", k=8), ['psT'], [f'PT{b}'])
        for cc in range(KC):
            h = cc // 2
            pa = cc // 4
            for mt in range(2):
                k.mm(psA[:, cc * 128:(cc + 1) * 128], V[:, mt, cc * 128:(cc + 1) * 128], PT[b][:, h * 2 + mt, :],
                     mt == 0, mt == 1, [f'V{mt}{cc // 4}', f'PT{b}'], [f'psA{pa}'])
        k.cp('dve', OT[b][:, 0:4, :], psA[:, 0:512].rearrange("p (k t) -> p k t", k=4), ['psA0'], [f'OT{b}_0'])
        k.cp('act', OT[b][:, 4:8, :], psA[:, 512:1024].rearrange("p (k t) -> p k t", k=4), ['psA1'], [f'OT{b}_1'])
        k.dma('sp', ht[hb][:], hin[rows, :], w=[f'ht{hb}'])
        yield
        for cg in range(2):
            for cc in range(KC):
                k.mm(psX[:, cg * 512:(cg + 1) * 512], OT[b][:, cc, :], Wo[:, cc, cg * 512:(cg + 1) * 512],
                     cc == 0, cc == KC - 1, [f'OT{b}_{cc // 4}', f'Wo{cc}'], [f'psX{cg}'])
        post_norm_res(k, [psX[:, 0:512], psX[:, 512:1024]], ['psX0', 'psX1'], ht[hb], f'ht{hb}',
                      g3bc, 'g3bc', [tmp[0][:], tmp[1][:]], ['tmp0', 'tmp1'], ss2[b % 2], rstd2[b % 2][:], junk, f'pn{b % 2}')
        k.dma('pool', hout[rows, :], ht[hb][:], r=[f'ht{hb}'], final=True)

    pipeline(tile, NTOK // 128)
    return k.finish()


def build_A2(NTOK, NC, fm, NF, k=None):
    k = k or K()
    NB = NTOK // 512
    x = k.din("x", [NTOK, D])
    gain = k.din("gain", [D])
    W = k.din("W", [D, NC])
    ident_d = k.din("ident", [128, 128])
    out = k.dout("out", [NTOK, NC])
    outT = k.dout("outT", [NF, NTOK])
    k.consts(ident_d)
    gc = k.gain_cols("gc", gain)
    Wb = k.load_weight("Wb", W, KC, NC, gcol=gc, gkey='gc', stage_cols=1408)
    cgs = [(c0, min(512, NC - c0)) for c0 in range(0, NC, 512)]
    def ring(nm, shape, n, dt=F32):
        return [k.sb(f"{nm}{j}", shape, dt) for j in range(n)]
    xt = ring("xt", [128, D], 6)
    xn = ring("xn", [128, D], 3, BF16)
    xT = [k.sb(f"xT{i}", [128, KC, 512], BF16) for i in range(2)]
    ot = [k.sb(f"ot{i}", [128, NC]) for i in range(2)]
    ft = [k.sb(f"ft{i}", [128, 512]) for i in range(2)]
    junk = k.sb("junk", [128, D], BF16)
    ss = ring("ss", [128, 1], 4)
    rstd = ring("rstd", [128, 1], 5)
    psT = k.ps("psT", [128, D], BF16)
    psO = [k.ps(f"psO{i}", [128, 512]) for i in range(4)]
    psF = [k.ps(f"psF{i}", [128, 512]) for i in range(2)]
    cnt = {'no': 0, 'nf': 0}

    def tile(i):
        blk, tt = divmod(i, 4)
        xb = blk % 2
        def T(lst, nm):
            j = i % len(lst)
            return lst[j], f'{nm}{j}'
        xt_, kxt = T(xt, 'xt'); xn_, kxn = T(xn, 'xn'); ss_, kss = T(ss, 'ss'); rs_, krs = T(rstd, 'rstd')
        k.dma('sp', xt_[:], x[i * 128:(i + 1) * 128, :], w=[kxt])
        yield
        k.act(junk[:], xt_[:], AF.Square, [kxt], ['junk', kss], accum_out=ss_[:])
        yield
        k.ts('dve', rs_[:], ss_[:], 1.0 / D, EPS, ALU.mult, ALU.add, [kss], [krs])
        yield
        k.act(rs_[:], rs_[:], AF.Sqrt, [krs], [krs])
        yield
        k.recip(rs_[:], rs_[:], [krs], [krs])
        k.ts('dve', xn_[:], xt_[:], rs_[:], None, ALU.mult, None, [kxt, krs], [kxn])
        yield
        for kc in range(KC):
            k.tr(psT[:, kc * 128:(kc + 1) * 128], xn_[:, kc * 128:(kc + 1) * 128], k.identb[:], [kxn], ['psT'])
        yield
        k.cp('act', xT[xb][:, :, tt * 128:(tt + 1) * 128], psT[:].rearrange("p (k t) -> p k t", k=KC), ['psT'], [f'xT{xb}'])
        yield
        if tt != 3:
            return
        for t2 in range(4):
            i2 = blk * 4 + t2
            b = i2 % 2
            for ci, (c0, cw) in enumerate(cgs):
                pb = cnt['no'] % 4
                cnt['no'] += 1
                for kc in range(KC):
                    k.mm(psO[pb][:, 0:cw], xT[xb][:, kc, t2 * 128:(t2 + 1) * 128], Wb[:, kc, c0:c0 + cw], kc == 0, kc == KC - 1,
                         [f'xT{xb}', f'Wb{kc}'], [f'psO{pb}'])
                k.cp('dve' if pb % 2 == 0 else 'act', ot[b][:, c0:c0 + cw], psO[pb][:, 0:cw], [f'psO{pb}'], [f'ot{b}_{pb % 2}'])
            k.dma('pool', out[i2 * 128:(i2 + 1) * 128, :], ot[b][:], r=[f'ot{b}_0', f'ot{b}_1'], final=True)
        for (c0, cw, r0) in fm:
            pf = cnt['nf'] % 2
            cnt['nf'] += 1
            for kc in range(KC):
                k.mm(psF[pf][0:cw, :], Wb[:, kc, c0:c0 + cw], xT[xb][:, kc, :], kc == 0, kc == KC - 1,
                     [f'Wb{kc}', f'xT{xb}'], [f'psF{pf}'])
            k.cp('dve' if pf == 0 else 'act', ft[pf][0:cw, :], psF[pf][0:cw, :], [f'psF{pf}'], [f'ft{pf}'])
            k.dma('pool', outT[r0:r0 + cw, blk * 512:(blk + 1) * 512], ft[pf][0:cw, :], r=[f'ft{pf}'], final=True)

    pipeline(tile, NTOK // 128)
    return k.finish()


def gen_GLA(L, k):
    NT = L // 128
    qT = k.din("qT", [128, L])
    kT = k.din("kT", [128, L])
    ktok = k.din("ktok", [L, 128])
    v = k.din("v", [L, 256])
    gate = k.din("gate", [L, 256])
    dlrT = k.din("dlrT", [16, L])
    w2 = k.din("w2", [16, 128])
    bdec = k.din("bdec", [1, 128])
    gn = k.din("gn", [256])
    triu_d = k.din("triu", [128, 128])
    trigt_d = k.din("trigt", [128, 128])
    oa = k.dout("oa", [L, 256])

    triu = k.sb("triu_s", [128, 128])
    trigt = k.sb("trigt_s", [128, 128])
    k.dma('sp', triu[:], triu_d, w=['triu'])
    k.dma('sp', trigt[:], trigt_d, w=['trigt'])
    w2s = k.sb("w2s", [16, 128])
    k.dma('sp', w2s[:], w2, w=['w2s'])
    bds = k.sb("bds", [1, 128])
    k.dma('sp', bds[:], bdec, w=['bds'])
    ones1 = k.sb("ones1", [1, 128])
    k.memset('dve', ones1[:], 1.0, ['ones1'])
    gnbc = k.bcast_row("gnbc", gn, 256)
    S = k.sb("S", [128, 128], mybir.dt.float32r)
    zS = k.sb("zS", [128, 128])
    k.memset('dve', zS[:], 0.0, ['zS'])
    k.cp('dve', S[:], zS[:], ['zS'], ['S'])
    rm = k.sb("rm", [128, 2])
    k.memset('dve', rm[:], 0.0, ['rm'])
    k.memset('dve', rm[0:64, 0:1], 0.125, ['rm'])
    k.memset('dve', rm[64:128, 1:2], 0.125, ['rm'])

    def ring(nm, shape, n, dt=F32):
        return [k.sb(f"{nm}{j}", shape, dt) for j in range(n)]
    FR_ = mybir.dt.float32r
    triur = k.sb("triur", [128, 128], FR_)
    trigtr = k.sb("trigtr", [128, 128], FR_)
    k.cp('dve', triur[:], triu[:], ['triu'], ['triur'])
    k.cp('dve', trigtr[:], trigt[:], ['trigt'], ['trigtr'])
    vr = ring("vr", [128, 256], 10, FR_)
    qTt, kTt, kt, gt = ring("qTt", [128, 128], 8), ring("kTt", [128, 128], 8), ring("kt", [128, 128], 8), ring("gt", [128, 256], 8)
    vt = ring("vt", [128, 256], 11)
    dt_ = ring("dt", [16, 128], 3)
    la = ring("la", [128, 128], 4, mybir.dt.float32r)
    sg = ring("sg", [128, 256], 16)
    EqT, EkT, Eks = ring("EqT", [128, 128], 7), ring("EkT", [128, 128], 3), ring("Eks", [128, 128], 3)
    qin, kin, kst = ring("qin", [128, 2, 128], 5, mybir.dt.float32r), ring("kin", [128, 128], 3, mybir.dt.float32r), ring("kst", [128, 128], 5, mybir.dt.float32r)
    sc0, sc1 = ring("sc0_", [128, 128], 3, mybir.dt.float32r), ring("sc1_", [128, 128], 3, mybir.dt.float32r)
    osr = ring("osr", [128, 256], 6)
    osb = ring("osb", [128, 256], 3)
    ss, rs = ring("ss", [128, 2], 4), ring("rs", [128, 2], 5)
    ot = ring("ot", [128, 256], 3)
    junk = k.sb("junk", [128, 128])
    psZ = [k.ps(f"psZ{j}", [128, 512]) for j in range(2)]
    psA = [k.ps(f"psA{j}", [128, 512]) for j in range(2)]
    psB = [k.ps(f"psB{j}", [128, 512]) for j in range(2)]
    psC = [k.ps(f"psC{j}", [128, 512]) for j in range(2)]

    def tile(i):
        rows = slice(i * 128, (i + 1) * 128)
        R = lambda lst: (lst[i % len(lst)], f'{lst[0].name if hasattr(lst[0], "name") else id(lst)}_{i % len(lst)}')
        def T(lst, nm):
            j = i % len(lst)
            return lst[j], f'{nm}{j}'
        q_, kq = T(qTt, 'qTt'); kT_, kkT = T(kTt, 'kTt'); kt_, kkt = T(kt, 'kt'); v_, kv = T(vt, 'vt'); g_, kg = T(gt, 'gt')
        d_, kd = T(dt_, 'dt'); la_, kla = T(la, 'la'); sg_, ksg = T(sg, 'sg')
        Eq, kEq = T(EqT, 'EqT'); Ek, kEk = T(EkT, 'EkT'); Es, kEs = T(Eks, 'Eks')
        qi, kqi = T(qin, 'qin'); ki, kki = T(kin, 'kin'); ks, kks = T(kst, 'kst')
        scs = [T(sc0, 'sc0_'), T(sc1, 'sc1_')]
        orw, korw = T(osr, 'osr'); ob_, kob = T(osb, 'osb'); ss_, kss = T(ss, 'ss'); rs_, krs = T(rs, 'rs'); ot_, kot = T(ot, 'ot')
        pz, kpz = psZ[i % 2], f'psZ{i % 2}'
        pa, kpa = psA[i % 2], f'psA{i % 2}'
        pb, kpb = psB[i % 2], f'psB{i % 2}'
        pc, kpc = psC[i % 2], f'psC{i % 2}'
        k.dma('sp', q_[:], qT[:, rows], w=[kq])
        k.dma('sp', kT_[:], kT[:, rows], w=[kkT])
        k.dma('sp', kt_[:], ktok[rows, :], w=[kkt])
        k.dma('sp', v_[:], v[rows, :], w=[kv])
        k.dma('sp', g_[:], gate[rows, :], w=[kg])
        k.dma('sp', d_[:], dlrT[:, rows], w=[kd])
        yield
        k.mm(pz[:, 0:128], d_[:], w2s[:], True, False, [kd, 'w2s'], [kpz])
        k.mm(pz[:, 0:128], ones1[:], bds[:], False, True, ['ones1', 'bds'], [kpz])
        yield
        k.act(la_[:], pz[:, 0:128], AF.Exp, [kpz], [kla], scale=-1.0)
        k.act(la_[:], la_[:].bitcast(F32), AF.Ln, [kla], [kla], bias=1.0)
        k.act(sg_[:], g_[:], AF.Exp, [kg], [ksg], scale=-1.0)
        vr_, kvr = T(vr, 'vr')
        k.cp('act', vr_[:], v_[:], [kv], [kvr])
        yield
        k.ts('dve', la_[:], la_[:].bitcast(F32), -1.0 / 16.0, None, ALU.mult, None, [kla], [kla])
        k.ts('dve', sg_[:], sg_[:], 1.0, None, ALU.add, None, [ksg], [ksg])
        k.recip(sg_[:], sg_[:], [ksg], [ksg])
        yield
        k.mm(pa[:, 0:128], la_[:], triur[:], True, True, [kla, 'triur'], [kpa])
        k.mm(pa[:, 128:256], trigtr[:], la_[:], True, True, [kla, 'trigtr'], [kpa])
        yield
        k.act(Eq[:], pa[:, 0:128], AF.Exp, [kpa], [kEq])
        k.act(Ek[:], pa[:, 0:128], AF.Exp, [kpa], [kEk], scale=-1.0)
        k.act(Es[:], pa[:, 128:256], AF.Exp, [kpa], [kEs])
        yield
        for h in range(2):
            k.stt(qi[:, h, :], q_[:], rm[:, h:h + 1], Eq[:], ALU.mult, ALU.mult, [kq, kEq, 'rm'], [kqi])
        k.tt('pool', ki[:], kT_[:], Ek[:], ALU.mult, [kkT, kEk], [kki])
        k.tt('pool', ks[:], kt_[:], Es[:], ALU.mult, [kkt, kEs], [kks])
        k.tt('pool', sg_[:], sg_[:], g_[:], ALU.mult, [ksg, kg], [ksg])
        yield
        for h in range(2):
            hp = slice(h * 64, (h + 1) * 64)
            k.mm(pb[:, h * 128:(h + 1) * 128], ki[:], qi[:, h, :], True, True, [kki, kqi], [kpb])
        yield
        for h in range(2):
            k.tt('dve', scs[h][0][:], pb[:, h * 128:(h + 1) * 128], triu[:], ALU.mult, [kpb, 'triu'], [scs[h][1]])
        yield
        for h in range(2):
            hp = slice(h * 64, (h + 1) * 64)
            k.mm(pc[:, h * 128:(h + 1) * 128], scs[h][0][:], vr_[:, h * 128:(h + 1) * 128], True, False, [scs[h][1], kvr], [kpc])
            k.mm(pc[:, h * 128:(h + 1) * 128], qi[:, h, :], S[:], False, True, [kqi, 'S'], [kpc])
        k.mm(pc[:, 256:512], ks[:], vr_[:], True, True, [kks, kvr], [kpc])
        yield
        for h in range(2):
            hp = slice(h * 64, (h + 1) * 64)
            k.stt(S[hp, :], S[hp, :].bitcast(F32), Eq[hp, 127:128], pc[hp, 256 + h * 128:256 + (h + 1) * 128], ALU.mult, ALU.add,
                  ['S', kEq, kpc], ['S'])
        k.cp('act', orw[:], pc[:, 0:256], [kpc], [korw])
        yield
        for h in range(2):
            k.act(junk[:], orw[:, h * 128:(h + 1) * 128], AF.Square, [korw], ['junk', kss], accum_out=ss_[:, h:h + 1])
        yield
        k.ts('dve', rs_[:], ss_[:], 1.0 / 128.0, EPS, ALU.mult, ALU.add, [kss], [krs])
        yield
        k.act(rs_[:], rs_[:], AF.Ln, [krs], [krs])
        k.act(rs_[:], rs_[:], AF.Exp, [krs], [krs], scale=-0.5)
        yield
        for h in range(2):
            hs = slice(h * 128, (h + 1) * 128)
            k.stt(ob_[:, hs], orw[:, hs], rs_[:, h:h + 1], gnbc[:, hs], ALU.mult, ALU.mult, [korw, krs, 'gnbc'], [kob])
        yield
        k.tt('pool', ot_[:], ob_[:], sg_[:], ALU.mult, [kob, ksg], [kot])
        k.dma('pool', oa[rows, :], ot_[:], r=[kot], final=True)

    yield from pipeline_gen(tile, NT)


def build_GLA(L, k=None):
    k = k or K()
    for _ in gen_GLA(L, k):
        pass
    return k.finish()


TWO_PI = 2.0 * math.pi
C1 = 6.28125
C2 = TWO_PI - 6.28125
PI_LO = 3.1415925


def range_sincos(k, x, xkey, shape, s_out, c_out, skey, ckey, pfx):
    if not hasattr(k, 'rr_cache'):
        k.rr_cache = {}
    if pfx not in k.rr_cache:
        k.rr_cache[pfx] = (k.sb(pfx + "kf", shape), k.sb(pfx + "ki", shape, I32), k.sb(pfx + "r", shape), k.sb(pfx + "m", shape))
    kf, ki, r, m = k.rr_cache[pfx]
    a = lambda t: t[:]
    K1, K2, K3, K4 = pfx + 'kf', pfx + 'ki', pfx + 'r', pfx + 'm'
    k.ts('dve', a(kf), x, 1.0 / TWO_PI, None, ALU.mult, None, [xkey], [K1])
    k.cp('dve', a(ki), a(kf), [K1], [K2])
    k.cp('dve', a(kf), a(ki), [K2], [K1])
    k.stt(a(r), a(kf), -C1, x, ALU.mult, ALU.add, [K1, xkey], [K3])
    k.stt(a(r), a(kf), -C2, a(r), ALU.mult, ALU.add, [K1, K3], [K3])
    k.ts('dve', a(m), a(r), math.pi, -TWO_PI, ALU.is_gt, ALU.mult, [K3], [K4])
    k.tt('dve', a(r), a(r), a(m), ALU.add, [K3, K4], [K3])
    k.ts('dve', a(m), a(r), -math.pi, TWO_PI, ALU.is_lt, ALU.mult, [K3], [K4])
    k.tt('dve', a(r), a(r), a(m), ALU.add, [K3, K4], [K3])
    k.ts('dve', a(kf), a(r), PI_LO, -PI_LO, ALU.min, ALU.max, [K3], [K1])
    k.act(s_out, a(kf), AF.Sin, [K1], [skey])
    k.ts('dve', a(r), a(r), math.pi / 2, None, ALU.add, None, [K3], [K3])
    k.ts('dve', a(m), a(r), math.pi, -TWO_PI, ALU.is_gt, ALU.mult, [K3], [K4])
    k.tt('dve', a(r), a(r), a(m), ALU.add, [K3, K4], [K3])
    k.ts('dve', a(kf), a(r), PI_LO, -PI_LO, ALU.min, ALU.max, [K3], [K1])
    k.act(c_out, a(kf), AF.Sin, [K1], [ckey])


def gen_S5(L, k):
    NT = L // 128
    NS = 1024
    uT = k.din("uT", [256, L])
    u = k.din("u", [L, 256])
    lam_re = k.din("lam_re", [NS])
    lam_im = k.din("lam_im", [NS])
    lstep = k.din("lstep", [NS])
    Bre = k.din("Bre", [2, 128, 512])
    Bim = k.din("Bim", [2, 128, 512])
    Cre = k.din("Cre", [8, 128, 32])
    Cim = k.din("Cim", [8, 128, 32])
    dsk = k.din("dsk", [256])
    triu_d = k.din("triu", [128, 128])
    iop_d = k.din("iota_p", [128, 1])
    iof_d = k.din("iota_f", [128, 128])
    y = k.dout("y", [L, 256])

    k.push_scope([("triu_s", [128, 128], F32), ("dbc", [128, 256], F32), ("BBr", [128, 2, 512], mybir.dt.float32r), ("BBi", [128, 2, 512], mybir.dt.float32r),
                  ("Pr", [128, NS], F32), ("Pi", [128, NS], F32), ("Qr", [128, 8, 128], F32), ("Qi", [128, 8, 128], F32),
                  ("L128r", [128, 8], F32), ("L128i", [128, 8], F32), ("Cr", [128, 8, 32], F32), ("nCi", [128, 8, 32], F32),
                  ("car_r", [128, 8], F32), ("car_i", [128, 8], F32), ("ntriu", [128, 128], mybir.dt.float32r), ("nCr", [128, 8, 32], mybir.dt.float32r), ("triur", [128, 128], mybir.dt.float32r), ("Crr", [128, 8, 32], mybir.dt.float32r), ("nCir", [128, 8, 32], mybir.dt.float32r)])
    triu = k.sb("triu_s", [128, 128])
    k.dma('sp', triu[:], triu_d, w=['triu'])
    iop = k.sb("iop", [128, 1])
    k.dma('sp', iop[:], iop_d, w=['iop'])
    negp = k.sb("negp", [128, 1])
    k.ts('dve', negp[:], iop[:], -1.0, None, ALU.mult, None, ['iop'], ['negp'])
    iof = k.sb("iof", [128, 128])
    k.dma('sp', iof[:], iof_d, w=['iof'])
    dbc = k.bcast_row("dbc", dsk, 256)
    R = [128, NS]
    lr = k.bcast_row("lr", lam_re, NS)
    li = k.bcast_row("li", lam_im, NS)
    dl = k.bcast_row("dl", lstep, NS)
    k.ts('dve', lr[:], lr[:], -1e-4, None, ALU.min, None, ['lr'], ['lr'])
    k.act(dl[:], dl[:], AF.Exp, ['dl'], ['dl'])
    a_ = k.sb("a_", R)
    th = k.sb("th", R)
    k.tt('dve', a_[:], lr[:], dl[:], ALU.mult, ['lr', 'dl'], ['a_'])
    k.tt('dve', th[:], li[:], dl[:], ALU.mult, ['li', 'dl'], ['th'])
    sn = k.sb("sn", R)
    cs = k.sb("cs", R)
    range_sincos(k, th[:], 'th', R, sn[:], cs[:], 'sn', 'cs', 'rr_')
    ea = k.sb("ea", R)
    k.act(ea[:], a_[:], AF.Exp, ['a_'], ['ea'])
    nr = k.sb("nr", R)
    ni = k.sb("ni", R)
    k.tt('dve', nr[:], ea[:], cs[:], ALU.mult, ['ea', 'cs'], ['nr'])
    k.ts('dve', nr[:], nr[:], -1.0, None, ALU.add, None, ['nr'], ['nr'])
    k.tt('dve', ni[:], ea[:], sn[:], ALU.mult, ['ea', 'sn'], ['ni'])
    den = k.sb("den", R)
    t0 = k.sb("t0", R)
    k.tt('dve', den[:], lr[:], lr[:], ALU.mult, ['lr'], ['den'])
    k.tt('dve', t0[:], li[:], li[:], ALU.mult, ['li'], ['t0'])
    k.tt('dve', den[:], den[:], t0[:], ALU.add, ['den', 't0'], ['den'])
    k.recip(den[:], den[:], ['den'], ['den'])
    gr = k.sb("gr", R)
    gi = k.sb("gi", R)
    k.tt('dve', gr[:], nr[:], lr[:], ALU.mult, ['nr', 'lr'], ['gr'])
    k.tt('dve', t0[:], ni[:], li[:], ALU.mult, ['ni', 'li'], ['t0'])
    k.tt('dve', gr[:], gr[:], t0[:], ALU.add, ['gr', 't0'], ['gr'])
    k.tt('dve', gr[:], gr[:], den[:], ALU.mult, ['gr', 'den'], ['gr'])
    k.tt('dve', gi[:], ni[:], lr[:], ALU.mult, ['ni', 'lr'], ['gi'])
    k.tt('dve', t0[:], nr[:], li[:], ALU.mult, ['nr', 'li'], ['t0'])
    k.tt('dve', gi[:], gi[:], t0[:], ALU.subtract, ['gi', 't0'], ['gi'])
    k.tt('dve', gi[:], gi[:], den[:], ALU.mult, ['gi', 'den'], ['gi'])
    Br = k.sb("Br", [128, 2, 512])
    Bi = k.sb("Bi", [128, 2, 512])
    BBr = k.sb("BBr", [128, 2, 512])
    BBi = k.sb("BBi", [128, 2, 512])
    for hc in range(2):
        k.dma('sp', Br[:, hc, :], Bre[hc], w=[f'Br{hc}'])
        k.dma('sp', Bi[:, hc, :], Bim[hc], w=[f'Bi{hc}'])
    grv = gr[:].rearrange("p (h n) -> p h n", h=2)
    giv = gi[:].rearrange("p (h n) -> p h n", h=2)
    t0v = t0[:].rearrange("p (h n) -> p h n", h=2)
    BK = ['Br0', 'Br1', 'Bi0', 'Bi1']
    k.tt('dve', BBr[:], grv, Br[:], ALU.mult, ['gr'] + BK, ['BBr'])
    k.tt('dve', t0v, giv, Bi[:], ALU.mult, ['gi'] + BK, ['t0'])
    k.tt('dve', BBr[:], BBr[:].bitcast(F32), t0v, ALU.subtract, ['BBr', 't0'], ['BBr'])
    k.tt('dve', BBi[:], grv, Bi[:], ALU.mult, ['gr'] + BK, ['BBi'])
    k.tt('dve', t0v, giv, Br[:], ALU.mult, ['gi'] + BK, ['t0'])
    k.tt('dve', BBi[:], BBi[:].bitcast(F32), t0v, ALU.add, ['BBi', 't0'], ['BBi'])
    ang = k.sb("ang", R)
    k.ts('dve', ang[:], th[:], iop[:, 0:1], None, ALU.mult, None, ['th', 'iop'], ['ang'])
    Pr = k.sb("Pr", R)
    Pi = k.sb("Pi", R)
    range_sincos(k, ang[:], 'ang', R, sn[:], cs[:], 'sn', 'cs', 'rr_')
    k.act(ea[:], a_[:], AF.Exp, ['a_', 'negp'], ['ea'], scale=negp[:, 0:1])
    k.tt('dve', Pr[:], ea[:], cs[:], ALU.mult, ['ea', 'cs'], ['Pr'])
    k.stt(Pi[:], ea[:], -1.0, sn[:], ALU.mult, ALU.mult, ['ea', 'sn'], ['Pi'])
    Cs = [128, 8]
    lrc = k.sb("lrc", Cs)
    lic = k.sb("lic", Cs)
    dlc = k.sb("dlc", Cs)
    cv = lambda d: d.rearrange("(blk p) -> p blk", p=128)
    k.dma('sp', lrc[:], cv(lam_re), w=['lrc'], allow_slow_non_contiguous=True)
    k.dma('sp', lic[:], cv(lam_im), w=['lic'], allow_slow_non_contiguous=True)
    k.dma('sp', dlc[:], cv(lstep), w=['dlc'], allow_slow_non_contiguous=True)
    k.ts('dve', lrc[:], lrc[:], -1e-4, None, ALU.min, None, ['lrc'], ['lrc'])
    k.act(dlc[:], dlc[:], AF.Exp, ['dlc'], ['dlc'])
    ac = k.sb("ac", Cs)
    thc = k.sb("thc", Cs)
    k.tt('dve', ac[:], lrc[:], dlc[:], ALU.mult, ['lrc', 'dlc'], ['ac'])
    k.tt('dve', thc[:], lic[:], dlc[:], ALU.mult, ['lic', 'dlc'], ['thc'])
    Qr = k.sb("Qr", [128, 8, 128])
    Qi = k.sb("Qi", [128, 8, 128])
    angv = ang[:].rearrange("p (b t) -> p b t", b=8)
    eav = ea[:].rearrange("p (b t) -> p b t", b=8)
    for blk in range(8):
        k.ts('dve', angv[:, blk, :], iof[:], thc[:, blk:blk + 1], None, ALU.mult, None, ['iof', 'thc'], ['ang'])
    range_sincos(k, ang[:], 'ang', R, sn[:], cs[:], 'sn', 'cs', 'rr_')
    for blk in range(8):
        k.act(eav[:, blk, :], iof[:], AF.Exp, ['iof', 'ac'], ['ea'], scale=ac[:, blk:blk + 1])
    k.tt('dve', Qr[:].rearrange("p b t -> p (b t)"), ea[:], cs[:], ALU.mult, ['ea', 'cs'], ['Qr'])
    k.tt('dve', Qi[:].rearrange("p b t -> p (b t)"), ea[:], sn[:], ALU.mult, ['ea', 'sn'], ['Qi'])
    a128 = k.sb("a128", Cs)
    s128 = k.sb("s128", Cs)
    c128 = k.sb("c128", Cs)
    L128r = k.sb("L128r", Cs)
    L128i = k.sb("L128i", Cs)
    k.ts('dve', a128[:], thc[:], 128.0, None, ALU.mult, None, ['thc'], ['a128'])
    range_sincos(k, a128[:], 'a128', Cs, s128[:], c128[:], 's128', 'c128', 'rc_')
    k.act(a128[:], ac[:], AF.Exp, ['ac', 's128', 'c128'], ['a128'], scale=128.0)
    k.tt('dve', L128r[:], a128[:], c128[:], ALU.mult, ['a128', 'c128'], ['L128r'])
    k.tt('dve', L128i[:], a128[:], s128[:], ALU.mult, ['a128', 's128'], ['L128i'])
    Cr = k.sb("Cr", [128, 8, 32])
    nCi = k.sb("nCi", [128, 8, 32])
    k.dma('sp', Cr[:], Cre.rearrange("b p c -> p b c"), w=['Cr'])
    k.dma('sp', nCi[:], Cim.rearrange("b p c -> p b c"), w=['nCi'])
    k.ts('dve', nCi[:], nCi[:], -1.0, None, ALU.mult, None, ['nCi'], ['nCi'])
    car_r = k.sb("car_r", Cs)
    car_i = k.sb("car_i", Cs)
    k.memset('dve', car_r[:], 0.0, ['car_r0', 'car_r1'])
    k.memset('dve', car_i[:], 0.0, ['car_i0', 'car_i1'])
    ntriu = k.sb("ntriu", [128, 128])
    k.ts('dve', ntriu[:], triu[:], -1.0, None, ALU.mult, None, ['triu'], ['ntriu'])
    nCr = k.sb("nCr", [128, 8, 32])
    k.ts('dve', nCr[:], Cr[:], -1.0, None, ALU.mult, None, ['Cr'], ['nCr'])
    triur = k.sb("triur", [128, 128])
    k.cp('dve', triur[:], triu[:], ['triu'], ['triur'])
    Crr = k.sb("Crr", [128, 8, 32])
    k.cp('dve', Crr[:], Cr[:], ['Cr'], ['Crr'])
    nCir = k.sb("nCir", [128, 8, 32])
    k.cp('dve', nCir[:], nCi[:], ['nCi'], ['nCir'])
    k.pop_scope()
    if hasattr(k, 'rr_cache'):
        del k.rr_cache
    def ring(nm, shape, n, dt=F32):
        return [k.sb(f"{nm}{j}", shape, dt) for j in range(n)]
    FR_ = mybir.dt.float32r
    uTt = ring("uTt", [128, 128], 3)
    uTr = ring("uTr", [128, 128], 3, FR_)
    ut = ring("ut", [128, 128], 5)
    yo = ring("yo", [128, 128], 9)
    m1, m2, m3, m4 = ring("m1_", [128, 512], 3, FR_), ring("m2_", [128, 512], 3, FR_), ring("m3_", [128, 512], 3, FR_), ring("m4_", [128, 512], 3, FR_)
    Xtr, Xti = ring("Xtr", [128, 512], 3), ring("Xti", [128, 512], 3)
    Gr, Gi = ring("Gr", [128, 4, 128], 4), ring("Gi", [128, 4, 128], 4)
    n1, n2, n3, n4 = ring("n1_", [128, 512], 3, FR_), ring("n2_", [128, 512], 3, FR_), ring("n3_", [128, 512], 3, FR_), ring("n4_", [128, 512], 3, FR_)
    Hr, Hi = ring("Hr", [128, 4, 128], 3), ring("Hi", [128, 4, 128], 3)
    cc1 = [k.sb(f"cc1_{h}", [128, 4]) for h in range(2)]
    cc2 = [k.sb(f"cc2_{h}", [128, 4]) for h in range(2)]
    psXr = k.ps("psXr", [128, 512])
    psXi = k.ps("psXi", [128, 512])
    psGr = k.ps("psGr", [128, 512])
    psGi = k.ps("psGi", [128, 512])
    psY = k.ps("psY", [128, 512])
    fl = lambda t: t[:].rearrange("p b t -> p (b t)")

    def item(j):
        i, hc = divmod(j, 2)
        rows = slice(i * 128, (i + 1) * 128)
        cs_ = slice(hc * 512, (hc + 1) * 512)
        bs = slice(hc * 4, (hc + 1) * 4)
        def T(lst, nm):
            q = j % len(lst)
            return lst[q], f'{nm}{q}'
        uT_, kuT = T(uTt, 'uTt'); uR_, kuR = T(uTr, 'uTr'); ut_, kut = T(ut, 'ut'); yo_, kyo = T(yo, 'yo')
        m1_, km1 = T(m1, 'm1'); m2_, km2 = T(m2, 'm2'); m3_, km3 = T(m3, 'm3'); m4_, km4 = T(m4, 'm4')
        Xr_, kXr = T(Xtr, 'Xtr'); Xi_, kXi = T(Xti, 'Xti'); Gr_, kGr = T(Gr, 'Gr'); Gi_, kGi = T(Gi, 'Gi')
        n1_, kn1 = T(n1, 'n1'); n2_, kn2 = T(n2, 'n2'); n3_, kn3 = T(n3, 'n3'); n4_, kn4 = T(n4, 'n4')
        Hr_, kHr = T(Hr, 'Hr'); Hi_, kHi = T(Hi, 'Hi')
        k.dma('sp', uT_[:], uT[hc * 128:(hc + 1) * 128, rows], w=[kuT])
        k.dma('sp', ut_[:], u[rows, hc * 128:(hc + 1) * 128], w=[kut])
        yield
        k.cp('act', uR_[:], uT_[:], [kuT], [kuR])
        yield
        k.mm(psXr[:], uR_[:], BBr[:, hc, :], True, True, [kuR, 'BBr'], ['psXr'])
        k.mm(psXi[:], uR_[:], BBi[:, hc, :], True, True, [kuR, 'BBi'], ['psXi'])
        yield
        k.tt('dve', m1_[:], psXr[:], Pr[:, cs_], ALU.mult, ['psXr', 'Pr'], [km1])
        k.tt('dve', m3_[:], psXr[:], Pi[:, cs_], ALU.mult, ['psXr', 'Pi'], [km3])
        k.tt('dve', m2_[:], psXi[:], Pi[:, cs_], ALU.mult, ['psXi', 'Pi'], [km2])
        k.tt('dve', m4_[:], psXi[:], Pr[:, cs_], ALU.mult, ['psXi', 'Pr'], [km4])
        yield
        k.tt('pool', yo_[:], ut_[:], dbc[:, hc * 128:(hc + 1) * 128], ALU.mult, [kut, 'dbc'], [kyo])
        yield
        for nb in range(4):
            ns = slice(nb * 128, (nb + 1) * 128)
            k.mm(psGr[:, ns], m1_[:, ns], triur[:], True, False, [km1, 'triur'], ['psGr'])
            k.mm(psGr[:, ns], m2_[:, ns], ntriu[:], False, True, [km2, 'ntriu'], ['psGr'])
            k.mm(psGi[:, ns], m3_[:, ns], triur[:], True, False, [km3, 'triur'], ['psGi'])
            k.mm(psGi[:, ns], m4_[:, ns], triur[:], False, True, [km4, 'triur'], ['psGi'])
        yield
        for nb in range(4):
            ns = slice(nb * 128, (nb + 1) * 128)
            k.act(Gr_[:, nb, :], psGr[:, ns], AF.Identity, ['psGr', f'car_r{hc}'], [kGr], bias=car_r[:, hc * 4 + nb:hc * 4 + nb + 1])
            k.act(Gi_[:, nb, :], psGi[:, ns], AF.Identity, ['psGi', f'car_i{hc}'], [kGi], bias=car_i[:, hc * 4 + nb:hc * 4 + nb + 1])
        yield
        gr127 = Gr_[:, :, 127]
        gi127 = Gi_[:, :, 127]
        CK = [f'cc1{hc}', f'cc2{hc}']
        k.tt('dve', cc1[hc][:], L128r[:, bs], gr127, ALU.mult, ['L128r', kGr], [CK[0]])
        k.tt('dve', cc2[hc][:], L128i[:, bs], gi127, ALU.mult, ['L128i', kGi], [CK[1]])
        k.tt('dve', car_r[:, bs], cc1[hc][:], cc2[hc][:], ALU.subtract, CK, [f'car_r{hc}'])
        k.tt('dve', cc1[hc][:], L128r[:, bs], gi127, ALU.mult, ['L128r', kGi], [CK[0]])
        k.tt('dve', cc2[hc][:], L128i[:, bs], gr127, ALU.mult, ['L128i', kGr], [CK[1]])
        k.tt('dve', car_i[:, bs], cc1[hc][:], cc2[hc][:], ALU.add, CK, [f'car_i{hc}'])
        yield
        qr = Qr[:, bs, :].rearrange("p b t -> p (b t)")
        qi = Qi[:, bs, :].rearrange("p b t -> p (b t)")
        k.tt('dve', n1_[:], fl(Gr_), qr, ALU.mult, [kGr, 'Qr'], [kn1])
        k.tt('dve', n2_[:], fl(Gi_), qi, ALU.mult, [kGi, 'Qi'], [kn2])
        k.tt('dve', n3_[:], fl(Gi_), qr, ALU.mult, [kGi, 'Qr'], [kn3])
        k.tt('dve', n4_[:], fl(Gr_), qi, ALU.mult, [kGr, 'Qi'], [kn4])
        yield
        for nb in range(4):
            blk = hc * 4 + nb
            ns = slice(nb * 128, (nb + 1) * 128)
            yo_s = psY[:, blk * 32:(blk + 1) * 32]
            k.mm(yo_s, n1_[:, ns], Crr[:, blk, :], True, False, [kn1, 'Crr'], ['psY'])
            k.mm(yo_s, n2_[:, ns], nCr[:, blk, :], False, False, [kn2, 'nCr'], ['psY'])
            k.mm(yo_s, n3_[:, ns], nCir[:, blk, :], False, False, [kn3, 'nCir'], ['psY'])
            k.mm(yo_s, n4_[:, ns], nCir[:, blk, :], False, True, [kn4, 'nCir'], ['psY'])
        yield
        k.tt('dve', yo_[:], yo_[:], psY[:, hc * 128:(hc + 1) * 128], ALU.add, [kyo, 'psY'], [kyo])
        yield
        k.dma('pool', y[rows, hc * 128:(hc + 1) * 128], yo_[:], r=[kyo], final=True)

    yield from pipeline_gen(item, 2 * NT)


def build_S5(L, k=None):
    k = k or K()
    for _ in gen_S5(L, k):
        pass
    return k.finish()


def s5_host_inputs(s, proj_u, prm):
    gs = slice(16 * s, 16 * s + 16)
    cs = slice(256 * s, 256 * s + 256)
    uc = np.ascontiguousarray(proj_u[:, cs])
    Bre = np.zeros((2, 128, 512), np.float32)
    Bim = np.zeros((2, 128, 512), np.float32)
    Cre = np.zeros((8, 128, 32), np.float32)
    Cim = np.zeros((8, 128, 32), np.float32)
    b_re, b_im = prm['s5_b_re'][gs], prm['s5_b_im'][gs]
    c_re, c_im = prm['s5_c_re'][gs], prm['s5_c_im'][gs]
    for g in range(16):
        hc, gl = g // 8, g % 8
        Bre[hc, gl * 16:(gl + 1) * 16, gl * 64:(gl + 1) * 64] = b_re[g].T
        Bim[hc, gl * 16:(gl + 1) * 16, gl * 64:(gl + 1) * 64] = b_im[g].T
        blk, g2 = g // 2, g % 2
        Cre[blk, g2 * 64:(g2 + 1) * 64, g2 * 16:(g2 + 1) * 16] = c_re[g].T
        Cim[blk, g2 * 64:(g2 + 1) * 64, g2 * 16:(g2 + 1) * 16] = c_im[g].T
    return dict(uT=np.ascontiguousarray(uc.T), u=uc,
                lam_re=np.ascontiguousarray(prm['s5_lambda_re'][gs].reshape(-1)),
                lam_im=np.ascontiguousarray(prm['s5_lambda_im'][gs].reshape(-1)),
                lstep=np.ascontiguousarray(np.repeat(prm['s5_log_step'][gs], 64)),
                Bre=Bre, Bim=Bim, Cre=Cre, Cim=Cim, dsk=np.ascontiguousarray(prm['s5_d'][cs]),
                triu=np.triu(np.ones((128, 128), np.float32)),
                iota_p=np.arange(128, dtype=np.float32).reshape(128, 1),
                iota_f=np.tile(np.arange(128, dtype=np.float32)[None], (128, 1)))


GELU_C = 1.5957691216057308


def gen_LRU(L, k):
    TT = 512
    NCH = L // TT
    xbT = k.din("xbT", [256, L])
    gateT = k.din("gateT", [256, L])
    cw_d = k.din("cw", [128, 2, 4])
    cb_d = k.din("cb", [128, 2])
    Wa_d = k.din("Wa", [2, 128, 128])
    Wx_d = k.din("Wx", [2, 128, 128])
    ba_d = k.din("ba", [128, 2])
    bx_d = k.din("bx", [128, 2])
    lam_d = k.din("lam", [128, 2])
    odT = k.dout("odT", [256, L])
    cw = k.sb("cw_s", [128, 2, 4])
    cb = k.sb("cb_s", [128, 2])
    Wa = k.sb("Wa_s", [128, 2, 128])
    Wx = k.sb("Wx_s", [128, 2, 128])
    ba = k.sb("ba_s", [128, 2])
    bx = k.sb("bx_s", [128, 2])
    c8 = k.sb("c8", [128, 2])
    k.dma('sp', cw[:], cw_d, w=['cw'])
    k.dma('sp', cb[:], cb_d, w=['cb'])
    k.dma('sp', Wa[:], Wa_d.rearrange("b p n -> p b n"), w=['Wa'])
    k.dma('sp', Wx[:], Wx_d.rearrange("b p n -> p b n"), w=['Wx'])
    k.dma('sp', ba[:], ba_d, w=['ba'])
    k.dma('sp', bx[:], bx_d, w=['bx'])
    k.dma('sp', c8[:], lam_d, w=['c8'])
    k.act(c8[:], c8[:], AF.Exp, ['c8'], ['c8'], scale=-1.0)
    k.act(c8[:], c8[:], AF.Ln, ['c8'], ['c8'], bias=1.0)
    k.ts('dve', c8[:], c8[:], -8.0, None, ALU.mult, None, ['c8'], ['c8'])
    hlast = k.sb("hlast", [128, 2])
    k.memset('dve', hlast[:], 0.0, ['hlast0', 'hlast1'])

    def ring(nm, shape, n):
        return [k.sb(f"{nm}{j}", shape) for j in range(n)]
    xh = ring("xh", [128, TT + 3], 3)
    gt = ring("gt", [128, TT], 8)
    xc = ring("xc", [128, TT], 5)
    r, ig, a, a2 = ring("r", [128, TT], 2), ring("ig", [128, TT], 3), ring("a", [128, TT], 5), ring("a2", [128, TT], 3)
    bt = ring("bt", [128, TT], 4)
    g2 = ring("g2", [128, TT], 5)
    h = ring("h", [128, TT], 2)
    ot = ring("ot", [128, TT], 3)
    psR = k.ps("psR", [128, TT])
    psI = k.ps("psI", [128, TT])

    def item(n):
        c, pb = divmod(n, 2)
        prow = slice(pb * 128, (pb + 1) * 128)
        def T(lst, nm):
            j = n % len(lst)
            return lst[j], f'{nm}{j}'
        xh_, kxh = T(xh, 'xh'); gt_, kgt = T(gt, 'gt'); xc_, kxc = T(xc, 'xc'); r_, kr = T(r, 'r'); ig_, kig = T(ig, 'ig')
        a_, ka = T(a, 'a'); a2_, ka2 = T(a2, 'a2'); bt_, kbt = T(bt, 'bt'); g2_, kg2 = T(g2, 'g2'); h_, kh = T(h, 'h'); ot_, kot = T(ot, 'ot')
        if c == 0:
            k.memset('dve', xh_[:, 0:3], 0.0, [kxh + 'h'])
            k.dma('sp', xh_[:, 3:TT + 3], xbT[prow, 0:TT], w=[kxh])
        else:
            k.dma('sp', xh_[:, 0:TT + 3], xbT[prow, c * TT - 3:(c + 1) * TT], w=[kxh, kxh + 'h'])
        k.dma('sp', gt_[:], gateT[prow, c * TT:(c + 1) * TT], w=[kgt])
        yield
        xk = [kxh, kxh + 'h']
        k.ts('dve', xc_[:], xh_[:, 3:TT + 3], cw[:, pb, 3:4], cb[:, pb:pb + 1], ALU.mult, ALU.add, xk + ['cw', 'cb'], [kxc])
        for j in (2, 1, 0):
            k.stt(xc_[:], xh_[:, j:j + TT], cw[:, pb, j:j + 1], xc_[:], ALU.mult, ALU.add, xk + ['cw', kxc], [kxc])
        yield
        k.mm(psR[:], Wa[:, pb, :], xc_[:], True, True, ['Wa', kxc], ['psR'])
        k.mm(psI[:], Wx[:, pb, :], xc_[:], True, True, ['Wx', kxc], ['psI'])
        yield
        k.act(r_[:], psR[:], AF.Sigmoid, ['psR', 'ba'], [kr], bias=ba[:, pb:pb + 1])
        k.act(ig_[:], psI[:], AF.Sigmoid, ['psI', 'bx'], [kig], bias=bx[:, pb:pb + 1])
        k.act(a_[:], r_[:], AF.Exp, [kr, 'c8'], [ka], scale=c8[:, pb:pb + 1])
        k.act(a2_[:], a_[:], AF.Square, [ka], [ka2])
        k.act(a2_[:], a2_[:], AF.Sqrt, [ka2], [ka2], scale=-1.0, bias=1.0)
        k.act(g2_[:], gt_[:], AF.Square, [kgt], [kg2])
        k.act(g2_[:], g2_[:], AF.Copy, [kg2], [kg2], scale=0.044715, bias=1.0)
        yield
        k.tt('dve', bt_[:], ig_[:], xc_[:], ALU.mult, [kig, kxc], [kbt])
        k.tt('dve', bt_[:], bt_[:], a2_[:], ALU.mult, [kbt, ka2], [kbt])
        k.tt('dve', g2_[:], g2_[:], gt_[:], ALU.mult, [kg2, kgt], [kg2])
        yield
        k.act(g2_[:], g2_[:], AF.Sigmoid, [kg2], [kg2], scale=GELU_C)
        yield
        k.P.op('dve', lambda e: e.tensor_tensor_scan(out=h_[:], data0=a_[:], data1=bt_[:], initial=hlast[:, pb:pb + 1],
                                                     op0=ALU.mult, op1=ALU.add),
               reads=[ka, kbt, f'hlast{pb}'], writes=[kh])
        k.cp('dve', hlast[:, pb:pb + 1], h_[:, TT - 1:TT], [kh], [f'hlast{pb}'])
        k.tt('dve', g2_[:], g2_[:], gt_[:], ALU.mult, [kg2, kgt], [kg2])
        k.tt('dve', ot_[:], h_[:], g2_[:], ALU.mult, [kh, kg2], [kot])
        yield
        k.dma('pool', odT[prow, c * TT:(c + 1) * TT], ot_[:], r=[kot], final=True)

    yield from pipeline_gen(item, 2 * NCH)


def build_LRU(L, k=None):
    k = k or K()
    for _ in gen_LRU(L, k):
        pass
    return k.finish()


def lru_host_inputs(s, xb, gate, prm):
    cs = slice(256 * s, 256 * s + 256)
    col = lambda v: np.ascontiguousarray(v[cs].reshape(2, 128).T)
    Wa = np.zeros((2, 128, 128), np.float32)
    Wx = np.zeros((2, 128, 128), np.float32)
    for pb in range(2):
        for bl in range(2):
            blk = 4 * s + 2 * pb + bl
            Wa[pb, bl * 64:(bl + 1) * 64, bl * 64:(bl + 1) * 64] = prm['lru_w_a'][blk]
            Wx[pb, bl * 64:(bl + 1) * 64, bl * 64:(bl + 1) * 64] = prm['lru_w_x'][blk]
    cw = np.ascontiguousarray(prm['lru_conv_w'][:, cs].reshape(4, 2, 128).transpose(2, 1, 0))
    return dict(xbT=np.ascontiguousarray(xb[:, cs].T), gateT=np.ascontiguousarray(gate[:, cs].T), cw=cw,
                cb=col(prm['lru_conv_b']), Wa=Wa, Wx=Wx, ba=col(prm['lru_b_a']), bx=col(prm['lru_b_x']),
                lam=col(prm['lru_lambda']))


GN_EPS = 64e-5
NLEV = 5


def build_RWKV(L, k=None, NH=4, fr=False, CH=64):
    k = k or K()
    NT = L // 128
    W = NH * 64
    NG = NH // 4
    FR = mybir.dt.float32r if fr else F32
    rd = (lambda ap: ap.bitcast(F32)) if fr else (lambda ap: ap)
    NCK = 128 // CH
    nlev = 5 if CH == 64 else 6
    frc = fr and CH == 128
    FRC = mybir.dt.float32r if frc else F32
    rdc = (lambda ap: ap.bitcast(F32)) if frc else (lambda ap: ap)
    lhc = (lambda ap: ap) if frc else rd
    prkv = [k.din(nm, [L, W]) for nm in ("pr", "pk", "pv")]
    mu1 = k.din("mu1", [3 * W])
    pls = [k.din("plw", [64, L]), k.din("pla", [64, L]), k.din("plg", [128, L])]
    mul = k.din("mul", [128, 3])
    w2 = k.din("w2", [64, W])
    a2 = k.din("a2", [64, W])
    g2 = k.din("g2", [128, W])
    vecs = k.din("vecs", [7, W])
    ident_d = k.din("ident", [128, 128])
    triw_d = k.din("triw", [3, 128, 128])
    mask5_d = k.din("mask5", [128, 640])
    rowm_d = k.din("rowm", [128, 2])
    oc = k.dout("oc", [L, W])

    k.consts(ident_d)
    triw = k.sb("triw_s", [128, 3, 128])
    k.dma('sp', triw[:], triw_d.rearrange("a p n -> p a n"), w=['triw'])
    mask5 = k.sb("mask5_s", [128, 640])
    k.dma('sp', mask5[:], mask5_d, w=['mask5'])
    rowm = k.sb("rowm_s", [128, 2])
    k.dma('sp', rowm[:], rowm_d, w=['rowm'])
    mu1bc = k.bcast_row("mu1bc", mu1, 3 * W)
    vb = [k.bcast_row(f"vb{i}", vecs[i], W) for i in range(7)]
    w0bc, a0bc, kkbc, kabc, rkbc, lngbc, lnbbc = vb
    VK = [f"vb{i}" for i in range(7)]
    muls = k.sb("muls", [128, 3])
    k.dma('sp', muls[:], mul, w=['muls'])
    w2s = k.sb("w2s", [64, W])
    a2s = k.sb("a2s", [64, W])
    k.dma('sp', w2s[:], w2, w=['w2s'])
    k.dma('sp', a2s[:], a2, w=['a2s'])
    g2s = k.sb("g2s", [128, W])
    k.dma('sp', g2s[:], g2, w=['g2s'])
    ST = [k.sb(f"ST{i}", [64, 64], FRC) for i in range(NH)]
    zt = k.sb("zt", [128, W])
    k.memset('dve', zt[:], 0.0, ['zt'])
    for i in range(NH):
        k.cp('dve', ST[i][:], zt[0:64, 0:64], ['zt'], [f'ST{i}'])
    P1s = k.sb("P1s", [128, W], FRC)
    Us = k.sb("Us", [128, W], FRC)
    k.cp('dve', P1s[:], zt[:], ['zt'], ['P1s'])
    k.cp('dve', Us[:], zt[:], ['zt'], ['Us'])

    pt = [k.sb(f"pt{i}", [128, 3 * W]) for i in range(2)]
    pp = [k.sb(f"pp{i}", [128, 3 * W]) for i in range(2)]
    lt = [k.sb(f"lt{i}", [128, 3, 128]) for i in range(2)]
    lp = [k.sb(f"lp{i}", [128, 3, 128]) for i in range(2)]
    for i_ in range(2):
        k.memset('pool', lt[i_][:], 0.0, [f'lt{i_}0', f'lt{i_}1', f'lt{i_}2'])
        k.memset('pool', lp[i_][:], 0.0, [f'lp{i_}0', f'lp{i_}1', f'lp{i_}2', f'lp{i_}z'])
    pm = k.sb("pm", [128, 3 * W])
    vr = k.sb("vr", [128, W], FR)
    lm = k.sb("lm", [128, 3, 128])
    sw = k.sb("sw", [128, W])
    av = k.sb("av", [128, W])
    gv = k.sb("gv", [128, W])
    kkr = k.sb("kkr", [128, W])
    sq = k.sb("sq", [128, W])
    s4 = k.sb("s4", [128, NH])
    rn = k.sb("rn", [128, NH])
    nkk = k.sb("nkk", [128, W])
    kmod = k.sb("kmod", [128, W])
    kka = k.sb("kka", [128, W])
    tmp = k.sb("tmp", [128, W])
    bon = k.sb("bon", [128, NH])
    E1 = k.sb("E1", [128, W])
    E2 = k.sb("E2", [128, W])
    E3 = k.sb("E3", [128, W])
    E4 = k.sb("E4", [128, W])
    E1T = k.sb("E1T", [64, NH, 128])
    At = k.sb("At", [128, W])
    Bs = k.sb("Bs", [128, W])
    Ks = k.sb("Ks", [128, W])
    Rt = k.sb("Rt", [128, W])
    Bfm = [k.sb(f"Bfm{c}", [128, W]) for c in range(2)]
    Kfm = [k.sb(f"Kfm{c}", [128, W]) for c in range(2)]
    FT = [k.sb(f"FT{h}", [64, 4, 128], FR) for h in range(NH)]
    A5 = [k.sb(f"A5_{h}", [128, 640], FR) for h in range(NH)]
    NL = [k.sb(f"NL_{h}", [128, 256], FR) for h in range(NH)]
    PQ = [k.sb(f"PQ_{h}", [128, 256], FR) for h in range(NH)]
    W1 = k.sb("W1", [128, W], FR)
    U1 = k.sb("U1", [128, W])
    ysb = k.sb("ysb", [128, W])
    yc = k.sb("yc", [128, W])
    m4 = k.sb("m4", [128, NH])
    r4 = k.sb("r4", [128, NH])
    ot = [k.sb(f"ot{i}", [128, W]) for i in range(2)]
    B = [k.ps(f"psB{i}", [128, 512]) for i in range(8)]
    bk = lambda i: f'psB{i}'
    v3 = lambda t: t.rearrange("p (h j) -> p h j", h=NH)
    bc4 = lambda t: t.unsqueeze(2).broadcast_to([128, NH, 64])

    for i in range(NT):
        b = i % 2
        rows = slice(i * 128, (i + 1) * 128)
        PK, PPK, LTK, LPK = [], [], [], []
        for q in range(3):
            cq = slice(q * W, (q + 1) * W)
            k.dma('sp', pt[b][:, cq], prkv[q][rows, :], w=[f'pt{b}{q}'])
            PK.append(f'pt{b}{q}')
            if i == 0:
                k.dma('sp', pp[b][1:128, cq], prkv[q][0:127, :], w=[f'pp{b}{q}'])
            else:
                k.dma('sp', pp[b][:, cq], prkv[q][i * 128 - 1:i * 128 + 127, :], w=[f'pp{b}{q}'])
            PPK.append(f'pp{b}{q}')
            nr = pls[q].shape[0]
            k.dma('sp', lt[b][0:nr, q, :], pls[q][:, rows], w=[f'lt{b}{q}'])
            LTK.append(f'lt{b}{q}')
            if i == 0:
                k.dma('sp', lp[b][0:nr, q, 1:128], pls[q][:, 0:127], w=[f'lp{b}{q}'])
            else:
                k.dma('sp', lp[b][0:nr, q, :], pls[q][:, i * 128 - 1:i * 128 + 127], w=[f'lp{b}{q}'])
            LPK.append(f'lp{b}{q}')
        if i == 0:
            k.memset('pool', pp[b][0:1, :], 0.0, [f'pp{b}z'])
            k.memset('pool', lp[b][:, :, 0:1], 0.0, [f'lp{b}z'])
            PPK.append(f'pp{b}z')
            LPK.append(f'lp{b}z')
        k.tt('pool', pm[:], pp[b][:], pt[b][:], ALU.subtract, PPK + PK, ['pm'])
        k.tt('pool', pm[:], pm[:], mu1bc[:], ALU.mult, ['pm', 'mu1bc'], ['pm'])
        k.tt('pool', pm[:], pm[:], pt[b][:], ALU.add, ['pm'] + PK, ['pm'])
        r_, k_, v_ = pm[:, 0:W], pm[:, W:2 * W], pm[:, 2 * W:3 * W]
        k.cp('act', vr[:], v_, ['pm'], ['vr'])
        LK = LTK + LPK
        k.tt('dve', lm[:], lp[b][:], lt[b][:], ALU.subtract, LK, ['lm'])
        for blk in range(3):
            k.stt(lm[:, blk, :], lm[:, blk, :], muls[:, blk:blk + 1], lt[b][:, blk, :], ALU.mult, ALU.add,
                  ['lm', 'muls'] + LK, ['lm'])
        k.act(lm[0:64, 0, :], lm[0:64, 0, :], AF.Tanh, ['lm'], ['lm'])
        k.act(lm[:, 2, :], lm[:, 2, :], AF.Sigmoid, ['lm'], ['lm'])
        k.mm(B[0][:, 0:W], lm[0:64, 0, :], w2s[:], True, True, ['lm', 'w2s'], [bk(0)])
        k.mm(B[1][:, 0:W], lm[0:64, 1, :], a2s[:], True, True, ['lm', 'a2s'], [bk(1)])
        k.mm(B[2][:, 0:W], lm[:, 2, :], g2s[:], True, True, ['lm', 'g2s'], [bk(2)])
        k.tt('dve', sw[:], B[0][:, 0:W], w0bc[:], ALU.add, [bk(0), VK[0]], ['sw'])
        k.act(sw[:], sw[:], AF.Sigmoid, ['sw'], ['sw'])
        k.tt('dve', av[:], B[1][:, 0:W], a0bc[:], ALU.add, [bk(1), VK[1]], ['av'])
        k.act(av[:], av[:], AF.Sigmoid, ['av'], ['av'])
        k.cp('act', gv[:], B[2][:, 0:W], [bk(2)], ['gv'])
        k.tt('pool', kkr[:], k_, kkbc[:], ALU.mult, ['pm', VK[2]], ['kkr'])
        k.tt('pool', sq[:], kkr[:], kkr[:], ALU.mult, ['kkr'], ['sq'])
        k.P.op('dve', lambda e: e.tensor_reduce(out=s4[:], in_=v3(sq[:]), axis=AX.X, op=ALU.add), reads=['sq'], writes=['s4'])
        k.act(s4[:], s4[:], AF.Sqrt, ['s4'], ['s4'])
        k.ts('dve', s4[:], s4[:], 1e-12, None, ALU.max, None, ['s4'], ['s4'])
        k.recip(rn[:], s4[:], ['s4'], ['rn'])
        k.ts('dve', rn[:], rn[:], -1.0, None, ALU.mult, None, ['rn'], ['rn'])
        k.tt('dve', v3(nkk[:]), v3(kkr[:]), bc4(rn[:]), ALU.mult, ['kkr', 'rn'], ['nkk'])
        k.stt(tmp[:], av[:], -1.0, kabc[:], ALU.add, ALU.mult, ['av', VK[3]], ['tmp'])
        k.stt(kmod[:], tmp[:], 1.0, k_, ALU.add, ALU.mult, ['tmp', 'pm'], ['kmod'])
        k.stt(kka[:], nkk[:], -1.0, av[:], ALU.mult, ALU.mult, ['nkk', 'av'], ['kka'])
        k.tt('pool', tmp[:], r_, kmod[:], ALU.mult, ['pm', 'kmod', 'tmp'], ['tmp'])
        k.tt('pool', tmp[:], tmp[:], rkbc[:], ALU.mult, ['tmp', VK[4]], ['tmp'])
        k.P.op('dve', lambda e: e.tensor_reduce(out=bon[:], in_=v3(tmp[:]), axis=AX.X, op=ALU.add), reads=['tmp'], writes=['bon'])
        k.mm(B[3][:, 0:W], triw[:, 0, :], sw[:], True, True, ['triw', 'sw'], [bk(3)])
        k.mm(B[4][:, 0:W], triw[:, 1, :], sw[:], True, True, ['triw', 'sw'], [bk(4)])
        k.mm(B[5][:, 0:W], triw[:, 2, :], sw[:], True, True, ['triw', 'sw'], [bk(5)])
        for h in range(NH):
            k.mm(B[6 + h // 4][0:64, (h % 4) * 128:(h % 4 + 1) * 128], sw[:, h * 64:(h + 1) * 64], triw[:, 0, :], True, True,
                 ['sw', 'triw'], [bk(6 + h // 4)])
        k.act(E1[:], B[3][:, 0:W], AF.Exp, [bk(3)], ['E1'])
        k.act(E2[:], B[3][:, 0:W], AF.Exp, [bk(3)], ['E2'], scale=-1.0)
        k.act(E3[:], B[4][:, 0:W], AF.Exp, [bk(4)], ['E3'])
        k.act(E4[:], B[5][:, 0:W], AF.Exp, [bk(5)], ['E4'])
        for g in range(NG):
            k.act(E1T[:, 4 * g:4 * g + 4, :].rearrange("p a t -> p (a t)"), B[6 + g][0:64, :], AF.Exp, [bk(6 + g)], ['E1T'])
        k.tt('dve', At[:], nkk[:], E3[:], ALU.mult, ['nkk', 'E3'], ['At'])
        k.tt('pool', Bs[:], kka[:], E2[:], ALU.mult, ['kka', 'E2'], ['Bs'])
        k.tt('dve', Ks[:], kmod[:], E2[:], ALU.mult, ['kmod', 'E2'], ['Ks'])
        k.tt('pool', Rt[:], r_, E1[:], ALU.mult, ['pm', 'E1'], ['Rt'])
        for c in range(NCK):
            k.stt(Bfm[c][:], kka[:], rowm[:, c:c + 1], E4[:], ALU.mult, ALU.mult, ['kka', 'E4', 'rowm'], [f'Bfm{c}'])
            k.stt(Kfm[c][:], kmod[:], rowm[:, c:c + 1], E4[:], ALU.mult, ALU.mult, ['kmod', 'E4', 'rowm'], [f'Kfm{c}'])
        HS = list(range(NH))
        for h in HS:
            cs_ = slice(h * 64, (h + 1) * 64)
            for q, (src, key) in enumerate([(At, 'At'), (Bs, 'Bs'), (Ks, 'Ks'), (Rt, 'Rt')]):
                k.tr(B[h][0:64, q * 128:(q + 1) * 128], src[:, cs_], k.identf[:], [key], [bk(h)])
        for h in HS:
            k.cp('act' if h % 2 else 'dve', FT[h][:].rearrange("p a t -> p (a t)"), B[h][0:64, :], [bk(h)], [f'FT{h}'])
        for h in HS:
            AtT, BsT, KsT, RtT = (FT[h][:, q, :] for q in range(4))
            o = lambda j: B[h][:, j * 128:(j + 1) * 128]
            k.mm(o(0), BsT, AtT, True, True, [f'FT{h}'], [bk(h)])
            k.mm(o(1), AtT, BsT, True, True, [f'FT{h}'], [bk(h)])
            k.mm(o(2), KsT, AtT, True, True, [f'FT{h}'], [bk(h)])
        for h in HS:
            k.tt('dve', A5[h][:, 0:384], B[h][:, 0:384], mask5[:, 0:384], ALU.mult, [bk(h), 'mask5'], [f'A5_{h}'])
        for h in HS:
            AtT, BsT, KsT, RtT = (FT[h][:, q, :] for q in range(4))
            k.mm(B[h][:, 0:128], BsT, RtT, True, True, [f'FT{h}'], [bk(h)])
            k.mm(B[h][:, 128:256], KsT, RtT, True, True, [f'FT{h}'], [bk(h)])
        for h in HS:
            k.tt('dve', A5[h][:, 384:640], B[h][:, 0:256], mask5[:, 384:640], ALU.mult, [bk(h), 'mask5'], [f'A5b_{h}'])
            k.cp('act', NL[h][:], rd(A5[h][:, 0:256]), [f'A5_{h}'], [f'NL_{h}'])
            k.tt('pool' if not fr else 'dve', PQ[h][:].rearrange("p (a n) -> p a n", a=2), rd(A5[h][:, 0:256]).rearrange("p (a n) -> p a n", a=2),
                 k.identf[:].unsqueeze(1).broadcast_to([128, 2, 128]), ALU.add, [f'A5_{h}', 'ident'], [f'PQ_{h}'])
        for lev in range(nlev):
            for h in HS:
                N_, L_ = NL[h][:, 0:128], NL[h][:, 128:256]
                k.mm(B[h][:, 0:128], L_, N_, True, True, [f'NL_{h}'], [bk(h)])
                k.mm(B[h][:, 128:256], N_, L_, True, True, [f'NL_{h}'], [bk(h)])
            for h in HS:
                k.cp('act', NL[h][:], B[h][:, 0:256], [bk(h)], [f'NL_{h}'])
            for h in HS:
                N_, L_ = NL[h][:, 0:128], NL[h][:, 128:256]
                P_, Q_ = PQ[h][:, 0:128], PQ[h][:, 128:256]
                k.mm(B[h][:, 256:384], Q_, N_, True, True, [f'NL_{h}', f'PQ_{h}'], [bk(h)])
                k.mm(B[h][:, 384:512], P_, L_, True, True, [f'NL_{h}', f'PQ_{h}'], [bk(h)])
            for h in HS:
                k.tt('dve', PQ[h][:], B[h][:, 256:512], rd(PQ[h][:]), ALU.add, [bk(h), f'PQ_{h}'], [f'PQ_{h}'])
        for h in range(NH):
            k.mm(B[0][:, h * 64:(h + 1) * 64], A5[h][:, 256:384], vr[:, h * 64:(h + 1) * 64], True, True, [f'A5_{h}', 'vr'], [bk(0)])
        k.cp('act', W1[:], B[0][:, 0:W], [bk(0)], ['W1'])
        for h in range(NH):
            k.mm(B[1][:, h * 64:(h + 1) * 64], PQ[h][:, 0:128], W1[:, h * 64:(h + 1) * 64], True, True,
                 [f'PQ_{h}', 'W1'], [bk(1)])
        k.cp('act', U1[:], B[1][:, 0:W], [bk(1)], ['U1'])
        vsrc = vr if frc else None
        for c in range(NCK):
            cr = slice(c * CH, (c + 1) * CH)
            for h in range(NH):
                k.mm(B[2][cr, h * 64:(h + 1) * 64], lhc(FT[h][:, 0, cr]), ST[h][:], True, True, [f'FT{h}', f'ST{h}'], [bk(2)])
            k.cp('act', P1s[cr, :], B[2][cr, 0:W], [bk(2)], ['P1s'])
            for h in range(NH):
                k.mm(B[3][cr, h * 64:(h + 1) * 64], lhc(PQ[h][:, cr]), P1s[:, h * 64:(h + 1) * 64], True, True,
                     [f'PQ_{h}', 'P1s'], [bk(3)])
            k.tt('dve', Us[cr, :], B[3][cr, 0:W], U1[cr, :], ALU.add, [bk(3), 'U1'], ['Us'])
            for h in range(NH):
                hc_ = slice(h * 64, (h + 1) * 64)
                vh = vr[:, hc_] if frc else pm[:, 2 * W + h * 64:2 * W + (h + 1) * 64]
                vk = 'vr' if frc else 'pm'
                k.mm(B[6][cr, hc_], lhc(FT[h][:, 3, cr]), ST[h][:], True, False, [f'FT{h}', f'ST{h}'], [bk(6)])
                k.mm(B[6][cr, hc_], lhc(A5[h][:, 384:512][:, cr]), Us[:, hc_], False, False, [f'A5b_{h}', 'Us'], [bk(6)])
                k.mm(B[6][cr, hc_], lhc(A5[h][:, 512:640][:, cr]), vh, False, True, [f'A5b_{h}', vk], [bk(6)])
            for h in range(NH):
                hc_ = slice(h * 64, (h + 1) * 64)
                vh = pm[:, 2 * W + h * 64:2 * W + (h + 1) * 64]
                k.mm(B[7][0:64, hc_], Bfm[c][:, hc_], rdc(Us[:, hc_]), True, False, [f'Bfm{c}', 'Us'], [bk(7)])
                k.mm(B[7][0:64, hc_], Kfm[c][:, hc_], vh, False, True, [f'Kfm{c}', 'pm'], [bk(7)])
            for h in range(NH):
                hc_ = slice(h * 64, (h + 1) * 64)
                k.stt(ST[h][:], rdc(ST[h][:]), E1T[:, h, (c + 1) * CH - 1:(c + 1) * CH], B[7][0:64, hc_], ALU.mult, ALU.add,
                      [f'ST{h}', 'E1T', bk(7)], [f'ST{h}'])
        k.cp('act', ysb[:], B[6][:, 0:W], [bk(6)], ['ysb'])
        k.P.op('dve', lambda e: e.tensor_reduce(out=m4[:], in_=v3(ysb[:]), axis=AX.X, op=ALU.add), reads=['ysb'], writes=['m4'])
        k.ts('dve', m4[:], m4[:], -1.0 / 64.0, None, ALU.mult, None, ['m4'], ['m4'])
        k.tt('dve', v3(yc[:]), v3(ysb[:]), bc4(m4[:]), ALU.add, ['ysb', 'm4'], ['yc'])
        k.tt('pool', sq[:], yc[:], yc[:], ALU.mult, ['yc'], ['sq'])
        k.P.op('dve', lambda e: e.tensor_reduce(out=r4[:], in_=v3(sq[:]), axis=AX.X, op=ALU.add), reads=['sq'], writes=['r4'])
        k.ts('dve', r4[:], r4[:], 1.0 / 64.0, GN_EPS, ALU.mult, ALU.add, ['r4'], ['r4'])
        k.act(r4[:], r4[:], AF.Sqrt, ['r4'], ['r4'])
        k.recip(r4[:], r4[:], ['r4'], ['r4'])
        k.tt('dve', v3(yc[:]), v3(yc[:]), bc4(r4[:]), ALU.mult, ['yc', 'r4'], ['yc'])
        k.tt('pool', yc[:], yc[:], lngbc[:], ALU.mult, ['yc', VK[5]], ['yc'])
        k.tt('pool', yc[:], yc[:], lnbbc[:], ALU.add, ['yc', VK[6]], ['yc'])
        k.tt('dve', v3(tmp[:]), v3(v_), bc4(bon[:]), ALU.mult, ['pm', 'bon', 'tmp'], ['tmp'])
        k.tt('pool', yc[:], yc[:], tmp[:], ALU.add, ['yc', 'tmp'], ['yc'])
        k.tt('dve', ot[b][:], yc[:], gv[:], ALU.mult, ['yc', 'gv'], [f'ot{b}'])
        k.dma('pool', oc[rows, :], ot[b][:], r=[f'ot{b}'], final=True)
    return k.finish()


def build_RWKVP(L, k=None, CH=64):
    NH, fr = 8, True
    k = k or K()
    NT = L // 128
    W = NH * 64
    NG = NH // 4
    FR = mybir.dt.float32r if fr else F32
    rd = (lambda ap: ap.bitcast(F32)) if fr else (lambda ap: ap)
    NCK = 128 // CH
    nlev = 5 if CH == 64 else 6
    frc = True
    FRC = mybir.dt.float32r if frc else F32
    rdc = (lambda ap: ap.bitcast(F32)) if frc else (lambda ap: ap)
    lhc = (lambda ap: ap) if frc else rd
    prkv = [k.din(nm, [L, W]) for nm in ("pr", "pk", "pv")]
    mu1 = k.din("mu1", [3 * W])
    pls = [k.din("plw", [64, L]), k.din("pla", [64, L]), k.din("plg", [128, L])]
    mul = k.din("mul", [128, 3])
    w2 = k.din("w2", [64, W])
    a2 = k.din("a2", [64, W])
    g2 = k.din("g2", [128, W])
    vecs = k.din("vecs", [7, W])
    ident_d = k.din("ident", [128, 128])
    triw_d = k.din("triw", [3, 128, 128])
    mask5_d = k.din("mask5", [128, 640])
    rowm_d = k.din("rowm", [128, 2])
    oc = k.dout("oc", [L, W])

    k.consts(ident_d)
    triw = k.sb("triw_s", [128, 3, 128])
    k.dma('sp', triw[:], triw_d.rearrange("a p n -> p a n"), w=['triw'])
    mask5 = k.sb("mask5_s", [128, 640])
    k.dma('sp', mask5[:], mask5_d, w=['mask5'])
    rowm = k.sb("rowm_s", [128, 2])
    k.dma('sp', rowm[:], rowm_d, w=['rowm'])
    mu1bc = k.bcast_row("mu1bc", mu1, 3 * W)
    vb = [k.bcast_row(f"vb{i}", vecs[i], W) for i in range(7)]
    w0bc, a0bc, kkbc, kabc, rkbc, lngbc, lnbbc = vb
    VK = [f"vb{i}" for i in range(7)]
    muls = k.sb("muls", [128, 3])
    k.dma('sp', muls[:], mul, w=['muls'])
    w2s = k.sb("w2s", [64, W])
    a2s = k.sb("a2s", [64, W])
    k.dma('sp', w2s[:], w2, w=['w2s'])
    k.dma('sp', a2s[:], a2, w=['a2s'])
    g2s = k.sb("g2s", [128, W])
    k.dma('sp', g2s[:], g2, w=['g2s'])
    ST = [k.sb(f"ST{i}", [64, 64], FRC) for i in range(NH)]
    zt = k.sb("zt", [128, W])
    k.memset('dve', zt[:], 0.0, ['zt'])
    for i in range(NH):
        k.cp('dve', ST[i][:], zt[0:64, 0:64], ['zt'], [f'ST{i}'])
    P1s = k.sb("P1s", [128, W], FRC)
    Us = k.sb("Us", [128, W], FRC)
    k.cp('dve', P1s[:], zt[:], ['zt'], ['P1s'])
    k.cp('dve', Us[:], zt[:], ['zt'], ['Us'])

    pt = [k.sb("pt0", [128, 3 * W])] * 2
    pp = [k.sb("pp0", [128, 3 * W])] * 2
    lt = [k.sb("lt0", [128, 3, 128])] * 2
    lp = [k.sb("lp0", [128, 3, 128])] * 2
    k.memset('pool', lt[0][:], 0.0, ['lt0', 'lt1', 'lt2'])
    k.memset('pool', lp[0][:], 0.0, ['lp0', 'lp1', 'lp2', 'lpz'])
    pm2 = [k.sb(f"pm{i_}", [128, 3 * W]) for i_ in range(2)]
    vr2 = [k.sb(f"vr{i_}", [128, W], FR) for i_ in range(2)]
    lm2 = [k.sb(f"lm{i_}", [128, 3, 128]) for i_ in range(2)]
    sw = k.sb("sw", [128, W])
    av = k.sb("av", [128, W])
    gv2 = [k.sb(f"gv{i_}", [128, W]) for i_ in range(2)]
    kkr = k.sb("kkr", [128, W])
    sq = k.sb("sq", [128, W])
    s4 = k.sb("s4", [128, NH])
    rn = k.sb("rn", [128, NH])
    nkk = k.sb("nkk", [128, W])
    kmod = k.sb("kmod", [128, W])
    kka = k.sb("kka", [128, W])
    tmp = k.sb("tmp", [128, W])
    bon2 = [k.sb(f"bon{i_}", [128, NH]) for i_ in range(2)]
    E1 = k.sb("E1", [128, W])
    E2 = k.sb("E2", [128, W])
    E3 = k.sb("E3", [128, W])
    E4 = k.sb("E4", [128, W])
    E1T2 = [k.sb(f"E1T{i_}", [64, NH, 128]) for i_ in range(2)]
    At2 = [k.sb(f"At{i_}", [128, W]) for i_ in range(2)]
    Bs2 = [k.sb(f"Bs{i_}", [128, W]) for i_ in range(2)]
    Ks2 = [k.sb(f"Ks{i_}", [128, W]) for i_ in range(2)]
    Rt2 = [k.sb(f"Rt{i_}", [128, W]) for i_ in range(2)]
    Bfm2 = [[k.sb(f"Bfm{p_}{c}", [128, W]) for c in range(NCK)] for p_ in range(2)]
    Kfm2 = [[k.sb(f"Kfm{p_}{c}", [128, W]) for c in range(NCK)] for p_ in range(2)]
    sqp = k.sb("sqp", [128, W])
    tmpp = k.sb("tmpp", [128, W])
    FT = [k.sb(f"FT{h}", [64, 4, 128], FR) for h in range(NH)]
    A5 = [k.sb(f"A5_{h}", [128, 640], FR) for h in range(NH)]
    NL = [k.sb(f"NL_{h}", [128, 256], FR) for h in range(NH)]
    PQ = [k.sb(f"PQ_{h}", [128, 128], FR) for h in range(NH)]
    W1 = k.sb("W1", [128, W], FR)
    U1 = k.sb("U1", [128, W])
    ysb = k.sb("ysb", [128, W])
    yc = k.sb("yc", [128, W])
    m4 = k.sb("m4", [128, NH])
    r4 = k.sb("r4", [128, NH])
    ot = [k.sb(f"ot{i}", [128, W]) for i in range(2)]
    B = [k.ps(f"psB{i}", [128, 512]) for i in range(8)]
    bk = lambda i: f'psB{i}'
    v3 = lambda t: t.rearrange("p (h j) -> p h j", h=NH)
    bc4 = lambda t: t.unsqueeze(2).broadcast_to([128, NH, 64])


    S0, S1, C0, C1 = 6, 7, 4, 5

    def tile(i):
        b = i % 2
        pm, lm = pm2[b], lm2[b]
        kpm, klm = f'pm{b}', f'lm{b}'
        At, Bs, Ks, Rt, gv, vr, bon, E1T, Bf, Kf = At2[b], Bs2[b], Ks2[b], Rt2[b], gv2[b], vr2[b], bon2[b], E1T2[b], Bfm2[b], Kfm2[b]
        kAt, kBs, kKs, kRt, kgv, kvr, kbon, kE1T, kBf, kKf = (f'{n_}{b}' for n_ in ('At', 'Bs', 'Ks', 'Rt', 'gv', 'vr', 'bon', 'E1T', 'Bf', 'Kf'))
        rows = slice(i * 128, (i + 1) * 128)
        PK, PPK, LTK, LPK = [], [], [], []
        for q in range(3):
            cq = slice(q * W, (q + 1) * W)
            k.dma('sp', pt[b][:, cq], prkv[q][rows, :], w=[f'pt{q}'])
            PK.append(f'pt{q}')
            if i == 0:
                k.dma('sp', pp[b][1:128, cq], prkv[q][0:127, :], w=[f'pp{q}'])
            else:
                k.dma('sp', pp[b][:, cq], prkv[q][i * 128 - 1:i * 128 + 127, :], w=[f'pp{q}'])
            PPK.append(f'pp{q}')
            nr = pls[q].shape[0]
            k.dma('sp', lt[b][0:nr, q, :], pls[q][:, rows], w=[f'lt{q}'])
            LTK.append(f'lt{q}')
            if i == 0:
                k.dma('sp', lp[b][0:nr, q, 1:128], pls[q][:, 0:127], w=[f'lp{q}'])
            else:
                k.dma('sp', lp[b][0:nr, q, :], pls[q][:, i * 128 - 1:i * 128 + 127], w=[f'lp{q}'])
            LPK.append(f'lp{q}')
        if i == 0:
            k.memset('pool', pp[b][0:1, :], 0.0, ['ppz'])
            k.memset('pool', lp[b][:, :, 0:1], 0.0, ['lpz'])
            PPK.append('ppz')
            LPK.append('lpz')
        k.tt('dve', pm[:], pp[b][:], pt[b][:], ALU.subtract, PPK + PK, [kpm])
        k.tt('dve', pm[:], pm[:], mu1bc[:], ALU.mult, [kpm, 'mu1bc'], [kpm])
        k.tt('dve', pm[:], pm[:], pt[b][:], ALU.add, [kpm] + PK, [kpm])
        r_, k_, v_ = pm[:, 0:W], pm[:, W:2 * W], pm[:, 2 * W:3 * W]
        LK = LTK + LPK
        k.tt('dve', lm[:], lp[b][:], lt[b][:], ALU.subtract, LK, [klm])
        for blk in range(3):
            k.stt(lm[:, blk, :], lm[:, blk, :], muls[:, blk:blk + 1], lt[b][:, blk, :], ALU.mult, ALU.add,
                  [klm, 'muls'] + LK, [klm])
        k.act(lm[0:64, 0, :], lm[0:64, 0, :], AF.Tanh, [klm], [klm])
        k.act(lm[:, 2, :], lm[:, 2, :], AF.Sigmoid, [klm], [klm])
        yield 'STAGE'
        k.cp('act', vr[:], v_, [kpm], [kvr])
        k.mm(B[S0][:, 0:W], lm[0:64, 0, :], w2s[:], True, True, [klm, 'w2s'], [bk(S0)])
        k.mm(B[S1][:, 0:W], lm[0:64, 1, :], a2s[:], True, True, [klm, 'a2s'], [bk(S1)])
        yield 'sub'
        k.tt('dve', sw[:], B[S0][:, 0:W], w0bc[:], ALU.add, [bk(S0), VK[0]], ['sw'])
        k.act(sw[:], sw[:], AF.Sigmoid, ['sw'], ['sw'])
        k.tt('dve', av[:], B[S1][:, 0:W], a0bc[:], ALU.add, [bk(S1), VK[1]], ['av'])
        k.act(av[:], av[:], AF.Sigmoid, ['av'], ['av'])
        yield 'sub'
        k.mm(B[S0][:, 0:W], lm[:, 2, :], g2s[:], True, True, [klm, 'g2s'], [bk(S0)])
        k.cp('act', gv[:], B[S0][:, 0:W], [bk(S0)], [kgv])
        yield 'sub'
        k.tt('dve', kkr[:], k_, kkbc[:], ALU.mult, [kpm, VK[2]], ['kkr'])
        k.tt('dve', sq[:], kkr[:], kkr[:], ALU.mult, ['kkr'], ['sq'])
        k.P.op('dve', lambda e: e.tensor_reduce(out=s4[:], in_=v3(sq[:]), axis=AX.X, op=ALU.add), reads=['sq'], writes=['s4'])
        k.act(s4[:], s4[:], AF.Sqrt, ['s4'], ['s4'])
        k.ts('dve', s4[:], s4[:], 1e-12, None, ALU.max, None, ['s4'], ['s4'])
        k.recip(rn[:], s4[:], ['s4'], ['rn'])
        k.ts('dve', rn[:], rn[:], -1.0, None, ALU.mult, None, ['rn'], ['rn'])
        k.tt('dve', v3(nkk[:]), v3(kkr[:]), bc4(rn[:]), ALU.mult, ['kkr', 'rn'], ['nkk'])
        k.stt(tmp[:], av[:], -1.0, kabc[:], ALU.add, ALU.mult, ['av', VK[3]], ['tmp'])
        k.stt(kmod[:], tmp[:], 1.0, k_, ALU.add, ALU.mult, ['tmp', kpm], ['kmod'])
        k.stt(kka[:], nkk[:], -1.0, av[:], ALU.mult, ALU.mult, ['nkk', 'av'], ['kka'])
        k.tt('dve', tmp[:], r_, kmod[:], ALU.mult, [kpm, 'kmod', 'tmp'], ['tmp'])
        k.tt('dve', tmp[:], tmp[:], rkbc[:], ALU.mult, ['tmp', VK[4]], ['tmp'])
        k.P.op('dve', lambda e: e.tensor_reduce(out=bon[:], in_=v3(tmp[:]), axis=AX.X, op=ALU.add), reads=['tmp'], writes=[kbon])
        yield 'sub'
        k.mm(B[S1][:, 0:W], triw[:, 0, :], sw[:], True, True, ['triw', 'sw'], [bk(S1)])
        k.mm(B[S0][:, 0:W], triw[:, 1, :], sw[:], True, True, ['triw', 'sw'], [bk(S0)])
        yield 'sub'
        k.act(E1[:], B[S1][:, 0:W], AF.Exp, [bk(S1)], ['E1'])
        k.act(E2[:], B[S1][:, 0:W], AF.Exp, [bk(S1)], ['E2'], scale=-1.0)
        k.act(E3[:], B[S0][:, 0:W], AF.Exp, [bk(S0)], ['E3'])
        k.mm(B[S1][:, 0:W], triw[:, 2, :], sw[:], True, True, ['triw', 'sw'], [bk(S1)])
        k.act(E4[:], B[S1][:, 0:W], AF.Exp, [bk(S1)], ['E4'])
        yield 'sub'
        for g in range(2):
            for hl in range(4):
                h = 4 * g + hl
                k.mm(B[S0 + g][0:64, hl * 128:(hl + 1) * 128], sw[:, h * 64:(h + 1) * 64], triw[:, 0, :], True, True,
                     ['sw', 'triw'], [bk(S0 + g)])
        yield 'sub'
        for g in range(2):
            k.act(E1T[:, 4 * g:4 * g + 4, :].rearrange("p a t -> p (a t)"), B[S0 + g][0:64, :], AF.Exp, [bk(S0 + g)], [kE1T])
        yield 'sub'
        k.tt('dve', At[:], nkk[:], E3[:], ALU.mult, ['nkk', 'E3'], [kAt])
        k.tt('dve', Bs[:], kka[:], E2[:], ALU.mult, ['kka', 'E2'], [kBs])
        k.tt('dve', Ks[:], kmod[:], E2[:], ALU.mult, ['kmod', 'E2'], [kKs])
        k.tt('dve', Rt[:], r_, E1[:], ALU.mult, [kpm, 'E1'], [kRt])
        for c in range(NCK):
            k.stt(Bf[c][:], kka[:], rowm[:, c:c + 1], E4[:], ALU.mult, ALU.mult, ['kka', 'E4', 'rowm'], [kBf])
            k.stt(Kf[c][:], kmod[:], rowm[:, c:c + 1], E4[:], ALU.mult, ALU.mult, ['kmod', 'E4', 'rowm'], [kKf])
        yield 'STAGE'
        for g in range(1):
            HS = list(range(8))
            for h in HS:
                hl = h
                cs_ = slice(h * 64, (h + 1) * 64)
                for q, (src, key) in enumerate([(At, kAt), (Bs, kBs), (Ks, kKs), (Rt, kRt)]):
                    k.tr(B[hl][0:64, q * 128:(q + 1) * 128], src[:, cs_], k.identf[:], [key], [bk(hl)])
            for h in HS:
                hl = h
                k.cp('act' if h % 2 else 'dve', FT[h][:].rearrange("p a t -> p (a t)"), B[hl][0:64, :], [bk(hl)], [f'FT{h}'])
            for h in HS:
                hl = h
                AtT, BsT, KsT, RtT = (FT[h][:, q, :] for q in range(4))
                k.mm(B[hl][:, 0:128], BsT, AtT, True, True, [f'FT{h}'], [bk(hl)])
                k.mm(B[hl][:, 128:256], AtT, BsT, True, True, [f'FT{h}'], [bk(hl)])
                k.mm(B[hl][:, 256:384], KsT, AtT, True, True, [f'FT{h}'], [bk(hl)])
            for h in HS:
                hl = h
                k.tt('dve', A5[h][:, 0:384], B[hl][:, 0:384], mask5[:, 0:384], ALU.mult, [bk(hl), 'mask5'], [f'A5_{h}'])
            for h in HS:
                hl = h
                AtT, BsT, KsT, RtT = (FT[h][:, q, :] for q in range(4))
                k.mm(B[hl][:, 0:128], BsT, RtT, True, True, [f'FT{h}'], [bk(hl)])
                k.mm(B[hl][:, 128:256], KsT, RtT, True, True, [f'FT{h}'], [bk(hl)])
            for h in HS:
                hl = h
                k.tt('dve', A5[h][:, 384:640], B[hl][:, 0:256], mask5[:, 384:640], ALU.mult, [bk(hl), 'mask5'], [f'A5b_{h}'])
                k.cp('act', NL[h][:], rd(A5[h][:, 0:256]), [f'A5_{h}'], [f'NL_{h}'])
                k.tt('dve', PQ[h][:, 0:128], rd(A5[h][:, 0:128]), k.identf[:], ALU.add, [f'A5_{h}', 'ident'], [f'PQ_{h}'])
            for lev in range(nlev):
                last = (lev == nlev - 1)
                for h in HS:
                    hl = h
                    N_, L_ = NL[h][:, 0:128], NL[h][:, 128:256]
                    if not last:
                        k.mm(B[hl][:, 0:128], L_, N_, True, True, [f'NL_{h}'], [bk(hl)])
                    k.mm(B[hl][:, 128:256], N_, L_, True, True, [f'NL_{h}'], [bk(hl)])
                for h in HS:
                    hl = h
                    if last:
                        k.cp('act', NL[h][:, 128:256], B[hl][:, 128:256], [bk(hl)], [f'NL_{h}'])
                    else:
                        k.cp('act', NL[h][:], B[hl][:, 0:256], [bk(hl)], [f'NL_{h}'])
                for h in HS:
                    hl = h
                    k.mm(B[hl][:, 256:384], NL[h][:, 128:256], PQ[h][:, 0:128], True, True, [f'NL_{h}', f'PQ_{h}'], [bk(hl)])
                for h in HS:
                    hl = h
                    k.tt('dve', PQ[h][:, 0:128], B[hl][:, 256:384], rd(PQ[h][:, 0:128]), ALU.add, [bk(hl), f'PQ_{h}'], [f'PQ_{h}'])
            yield 'GROUP'
        for h in range(NH):
            k.mm(B[C0][:, h * 64:(h + 1) * 64], A5[h][:, 256:384], vr[:, h * 64:(h + 1) * 64], True, True, [f'A5_{h}', kvr], [bk(C0)])
        yield 'sub'
        k.cp('act', W1[:], B[C0][:, 0:W], [bk(C0)], ['W1'])
        yield 'sub'
        for h in range(NH):
            k.mm(B[C1][:, h * 64:(h + 1) * 64], PQ[h][:, 0:128], W1[:, h * 64:(h + 1) * 64], True, True,
                 [f'PQ_{h}', 'W1'], [bk(C1)])
        yield 'sub'
        k.cp('act', U1[:], B[C1][:, 0:W], [bk(C1)], ['U1'])
        yield 'sub'
        for c in range(NCK):
            cr = slice(c * CH, (c + 1) * CH)
            for h in range(NH):
                k.mm(B[C0][:, h * 64:(h + 1) * 64], FT[h][:, 0, :], ST[h][:], True, True, [f'FT{h}', f'ST{h}'], [bk(C0)])
            yield 'sub'
            k.cp('act', P1s[cr, :], B[C0][cr, 0:W], [bk(C0)], ['P1s'])
            yield 'sub'
            for h in range(NH):
                k.mm(B[C0][:, h * 64:(h + 1) * 64], PQ[h][:, :], P1s[:, h * 64:(h + 1) * 64], True, True,
                     [f'PQ_{h}', 'P1s'], [bk(C0)])
            yield 'sub'
            k.tt('dve', Us[cr, :], B[C0][cr, 0:W], U1[cr, :], ALU.add, [bk(C0), 'U1'], ['Us'])
            yield 'sub'
            for h in range(NH):
                hc_ = slice(h * 64, (h + 1) * 64)
                k.mm(B[C0][:, hc_], FT[h][:, 3, :], ST[h][:], True, False, [f'FT{h}', f'ST{h}'], [bk(C0)])
                k.mm(B[C0][:, hc_], A5[h][:, 384:512], Us[:, hc_], False, False, [f'A5b_{h}', 'Us'], [bk(C0)])
                k.mm(B[C0][:, hc_], A5[h][:, 512:640], vr[:, hc_], False, True, [f'A5b_{h}', kvr], [bk(C0)])
            yield 'sub'
            k.cp('act', ysb[cr, :], B[C0][cr, 0:W], [bk(C0)], ['ysb'])
            for h in range(NH):
                hc_ = slice(h * 64, (h + 1) * 64)
                k.mm(B[C1][0:64, hc_], Bf[c][:, hc_], rdc(Us[:, hc_]), True, False, [kBf, 'Us'], [bk(C1)])
                k.mm(B[C1][0:64, hc_], Kf[c][:, hc_], rd(vr[:, hc_]), False, True, [kKf, kvr], [bk(C1)])
            yield 'sub'
            for h in range(NH):
                hc_ = slice(h * 64, (h + 1) * 64)
                k.stt(ST[h][:], rdc(ST[h][:]), E1T[:, h, (c + 1) * CH - 1:(c + 1) * CH], B[C1][0:64, hc_], ALU.mult, ALU.add,
                      [f'ST{h}', kE1T, bk(C1)], [f'ST{h}'])
        k.P.op('dve', lambda e: e.tensor_reduce(out=m4[:], in_=v3(ysb[:]), axis=AX.X, op=ALU.add), reads=['ysb'], writes=['m4'])
        k.ts('dve', m4[:], m4[:], -1.0 / 64.0, None, ALU.mult, None, ['m4'], ['m4'])
        k.tt('dve', v3(yc[:]), v3(ysb[:]), bc4(m4[:]), ALU.add, ['ysb', 'm4'], ['yc'])
        k.tt('dve', sqp[:], yc[:], yc[:], ALU.mult, ['yc'], ['sqp'])
        k.P.op('dve', lambda e: e.tensor_reduce(out=r4[:], in_=v3(sqp[:]), axis=AX.X, op=ALU.add), reads=['sqp'], writes=['r4'])
        k.ts('dve', r4[:], r4[:], 1.0 / 64.0, GN_EPS, ALU.mult, ALU.add, ['r4'], ['r4'])
        k.act(r4[:], r4[:], AF.Sqrt, ['r4'], ['r4'])
        k.recip(r4[:], r4[:], ['r4'], ['r4'])
        k.tt('dve', v3(yc[:]), v3(yc[:]), bc4(r4[:]), ALU.mult, ['yc', 'r4'], ['yc'])
        k.tt('dve', yc[:], yc[:], lngbc[:], ALU.mult, ['yc', VK[5]], ['yc'])
        k.tt('dve', yc[:], yc[:], lnbbc[:], ALU.add, ['yc', VK[6]], ['yc'])
        k.tt('dve', v3(tmpp[:]), v3(rd(vr[:])), bc4(bon[:]), ALU.mult, [kvr, kbon], ['tmpp'])
        k.tt('dve', yc[:], yc[:], tmpp[:], ALU.add, ['yc', 'tmpp'], ['yc'])
        k.tt('dve', ot[b][:], yc[:], gv[:], ALU.mult, ['yc', kgv], [f'ot{b}'])
        k.dma('pool', oc[rows, :], ot[b][:], r=[f'ot{b}'], final=True)

    gens = {}
    done = set()

    def adv(j):
        try:
            return next(gens[j])
        except StopIteration:
            done.add(j)
            return 'END'

    for step in range(NT + 2):
        jb = step - 2
        if 0 <= jb < NT:
            n_g = 0
            while n_g < 1:
                if adv(jb) == 'GROUP':
                    n_g += 1
        if step < NT:
            gens[step] = tile(step)
            while adv(step) != 'STAGE':
                pass
        ja = step - 1
        a_live = 0 <= ja < NT
        b_live = 0 <= jb < NT
        while a_live or b_live:
            if a_live:
                if adv(ja) == 'STAGE':
                    a_live = False
            if b_live:
                if adv(jb) == 'END':
                    b_live = False
    return k.finish()


def rwkv_consts(CH=64):
    c = -math.exp(-0.5)
    blk = np.kron(np.eye(128 // CH), np.ones((CH, CH)))
    s_idx = np.arange(128)[:, None]
    t_idx = np.arange(128)[None, :]
    triw = np.stack([c * blk * (s_idx <= t_idx), c * blk * (s_idx < t_idx), c * blk * (s_idx > t_idx)]).astype(np.float32)
    lt_, le_, gt_ = blk * (s_idx < t_idx), blk * (s_idx <= t_idx), blk * (t_idx < s_idx)
    mask5 = np.concatenate([lt_, gt_, lt_, le_, le_], 1).astype(np.float32)
    rowm = np.stack([(np.arange(128) < 64), (np.arange(128) >= 64)], 1).astype(np.float32) if CH == 64 else np.ones((128, 2), np.float32)
    return dict(ident=np.eye(128, dtype=np.float32), triw=triw, mask5=mask5, rowm=rowm)


def rwkv_host_inputs(s, p_rwkv, prm, NH=4, CH=64):
    L = p_rwkv.shape[0]
    cs = slice(64 * NH * s, 64 * NH * (s + 1))
    r_, w1, k_, v_, a1, g1 = np.split(p_rwkv, np.cumsum([512, 64, 512, 512, 64])[:5], axis=-1)
    mu = prm['rwkv_mu']
    mur, muw1, muk, muv, mua1, mug1 = np.split(mu, np.cumsum([512, 64, 512, 512, 64])[:5])
    zm = np.zeros(64, np.float32)
    mul = np.concatenate([muw1, zm, mua1, zm, mug1]).reshape(3, 128).T
    vecs = np.stack([prm['rwkv_w0'][cs], prm['rwkv_a0'][cs], prm['rwkv_k_k'][cs], prm['rwkv_k_a'][cs],
                     prm['rwkv_r_k'].reshape(-1)[cs], prm['rwkv_ln_gain'][cs], prm['rwkv_ln_bias'][cs]])
    c_ = np.ascontiguousarray
    d = dict(pr=c_(r_[:, cs]), pk=c_(k_[:, cs]), pv=c_(v_[:, cs]),
             mu1=c_(np.concatenate([mur[cs], muk[cs], muv[cs]])),
             plw=c_(w1.T), pla=c_(a1.T), plg=c_(g1.T), mul=c_(mul),
             w2=c_(prm['rwkv_w2'][:, cs]), a2=c_(prm['rwkv_a2'][:, cs]),
             g2=c_(prm['rwkv_g2'][:, cs]), vecs=c_(vecs))
    d.update(rwkv_consts(CH))
    return d


FM0 = [(0, 128, 0), (128, 128, 128), (256, 128, 256), (384, 128, 384), (1536, 16, 512)] + \
      [(1552 + j * 128, 128, 528 + j * 128) for j in range(4)]
NF0 = 1040
FM1 = [(512, 64, 0), (1600, 64, 64), (1664, 128, 128)] + [(1792 + j * 128, 128, 256 + j * 128) for j in range(8)]
NF1 = 1280


def host_params(inp):
    c_ = lambda a: np.ascontiguousarray(np.asarray(a), dtype=np.float32)
    P = {}
    P['ident'] = np.eye(128, dtype=np.float32)
    P['triu'] = np.triu(np.ones((128, 128), np.float32))
    P['trigt'] = np.tril(np.ones((128, 128), np.float32), -1)
    for l in range(2):
        for j in range(7):
            P[f'g{l}_{j}'] = c_(inp['norm_gain'][l][j])
        for nm in ('xa_wq', 'xa_wk', 'xa_wv', 'xa_wo', 'mlp_w1', 'mlp_w2'):
            P[f'{nm}{l}'] = c_(inp[nm][l])
    P['w_in0'] = c_(inp['ab_w_in'][0])
    P['w_in1'] = c_(inp['cd_w_in'][0])
    P['w_out0'] = c_(inp['ab_w_out'][0])
    P['w_out1'] = c_(inp['cd_w_out'][0])
    P['wglu'] = c_(inp['s5_w_glu'][0])
    P['bglu'] = c_(inp['s5_b_glu'][0])
    prm0 = {k_: np.asarray(inp[k_][0]) for k_ in inp if k_.startswith('s5_') or k_.startswith('gla_')}
    prm1 = {k_: np.asarray(inp[k_][0]) for k_ in inp if k_.startswith('rwkv_') or k_.startswith('lru_')}
    for s in range(2):
        cs = slice(s * 128, (s + 1) * 128)
        P[f'gla_w2_{s}'] = c_(prm0['gla_w_decay2'][:, cs])
        P[f'gla_bd_{s}'] = c_(prm0['gla_b_decay'][None, cs])
        P[f'gla_gn_{s}'] = c_(prm0['gla_norm_gain'][2 * s:2 * s + 2].reshape(256))
        d = s5_host_inputs(s, np.zeros((2, 512), np.float32), prm0)
        for nm in ('lam_re', 'lam_im', 'lstep', 'Bre', 'Bim', 'Cre', 'Cim', 'dsk'):
            P[f's5_{nm}_{s}'] = c_(d[nm])
        P['iota_p'] = c_(d['iota_p'])
        P['iota_f'] = c_(d['iota_f'])
        if s == 0:
            d = rwkv_host_inputs(0, np.zeros((2, 1792), np.float32), prm1, 8, 64)
            for nm in ('mu1', 'mul', 'w2', 'a2', 'g2', 'vecs'):
                P[f'rw_{nm}'] = c_(d[nm])
            for nm in ('triw', 'mask5', 'rowm'):
                P[f'rw_{nm}'] = c_(d[nm])
        d = lru_host_inputs(s, np.zeros((2, 512), np.float32), np.zeros((2, 512), np.float32), prm1)
        for nm in ('cw', 'cb', 'Wa', 'Wx', 'ba', 'bx', 'lam'):
            P[f'lru_{nm}_{s}'] = c_(d[nm])
    return P


def build_fused(P, L):
    k = K(fused=True)
    X = {nm: k.xin(nm, a.shape) for nm, a in P.items()}
    x = k.xin('x', [L, D])
    mem = k.xin('mem', [256, D])
    out = k.xout('out', [L, D])
    proj0 = k.scratch('proj0', [L, 2064])
    PT0 = k.scratch('PT0', [NF0, L])
    proj1 = k.scratch('proj1', [L, 2816])
    PT1 = k.scratch('PT1', [NF1, L])
    o = k.scratch('o', [L, D])
    odT = k.scratch('odT', [512, L])
    h1 = k.scratch('h1', [L, D])
    h2 = k.scratch('h2', [L, D])
    h3 = k.scratch('h3', [L, D])

    def cblock(l, hin, hout, glu, ob_fm):
        io = dict(oa=o[:, 0:512], hin=hin, wout=X[f'w_out{l}'], g1=X[f'g{l}_1'], ident=X['ident'], hout=h1)
        if ob_fm:
            io['obT'] = odT
        else:
            io['ob'] = o[:, 512:1024]
        if glu:
            io.update(wglu=X['wglu'], bglu=X['bglu'])
        k.begin_phase(f'C1_{l}', io)
        build_C1(L, glu, k=k, ob_fm=ob_fm)
        k.begin_phase(f'C2_{l}', dict(hin=h1, mem=mem, wq=X[f'xa_wq{l}'], wk=X[f'xa_wk{l}'], wv=X[f'xa_wv{l}'], wo=X[f'xa_wo{l}'],
                                      g2=X[f'g{l}_2'], g3=X[f'g{l}_3'], g6=X[f'g{l}_6'], ident=X['ident'], hout=h2))
        build_C2(L, k=k)
        k.begin_phase(f'C3_{l}', dict(hin=h2, w1=X[f'mlp_w1{l}'], w2=X[f'mlp_w2{l}'], g4=X[f'g{l}_4'], g5=X[f'g{l}_5'],
                                      ident=X['ident'], hout=hout))
        build_C3(L, k=k)

    k.begin_phase('A0', dict(x=x, gain=X['g0_0'], W=X['w_in0'], ident=X['ident'], out=proj0, outT=PT0))
    build_A2(L, 2064, FM0, NF0, k=k)
    for s in range(2):
        io_g = dict(qT=PT0[s * 128:(s + 1) * 128, :], kT=PT0[256 + s * 128:256 + (s + 1) * 128, :],
                    ktok=proj0[:, 256 + s * 128:256 + (s + 1) * 128], v=proj0[:, 512 + s * 256:512 + (s + 1) * 256],
                    gate=proj0[:, 1024 + s * 256:1024 + (s + 1) * 256], dlrT=PT0[512:528, :],
                    w2=X[f'gla_w2_{s}'], bdec=X[f'gla_bd_{s}'], gn=X[f'gla_gn_{s}'], triu=X['triu'],
                    trigt=X['trigt'], oa=o[:, s * 256:(s + 1) * 256])
        k.begin_phase(f'GLA{s}', io_g)
        build_GLA(L, k=k)
    for s in range(2):
        io_s = dict(uT=PT0[528 + s * 256:528 + (s + 1) * 256, :], u=proj0[:, 1552 + s * 256:1552 + (s + 1) * 256],
                    triu=X['triu'], iota_p=X['iota_p'], iota_f=X['iota_f'], y=o[:, 512 + s * 256:512 + (s + 1) * 256])
        for nm in ('lam_re', 'lam_im', 'lstep', 'Bre', 'Bim', 'Cre', 'Cim', 'dsk'):
            io_s[nm] = X[f's5_{nm}_{s}']
        k.begin_phase(f'S5{s}', io_s)
        build_S5(L, k=k)
    cblock(0, x, h3, True, False)
    k.begin_phase('A1', dict(x=h3, gain=X['g1_0'], W=X['w_in1'], ident=X['ident'], out=proj1, outT=PT1))
    build_A2(L, 2816, FM1, NF1, k=k)
    io = dict(pr=proj1[:, 0:512], pk=proj1[:, 576:1088], pv=proj1[:, 1088:1600], plw=PT1[0:64, :], pla=PT1[64:128, :],
              plg=PT1[128:256, :], ident=X['ident'], triw=X['rw_triw'], mask5=X['rw_mask5'], rowm=X['rw_rowm'], oc=o[:, 0:512])
    for nm in ('mu1', 'mul', 'w2', 'a2', 'g2', 'vecs'):
        io[nm] = X[f'rw_{nm}']
    k.begin_phase('RW', io)
    build_RWKVP(L, k=k, CH=64)
    streams = []
    for s in range(2):
        io = dict(xbT=PT1[256 + s * 256:256 + (s + 1) * 256, :], gateT=PT1[768 + s * 256:768 + (s + 1) * 256, :],
                  odT=odT[s * 256:(s + 1) * 256, :])
        for nm in ('cw', 'cb', 'Wa', 'Wx', 'ba', 'bx', 'lam'):
            io[nm] = X[f'lru_{nm}_{s}']
        streams.append((f'l{s}_', io, lambda kk: gen_LRU(L, kk)))
    k.begin_phase('LRU', {})
    run_streams(k, streams)
    k.finish()
    cblock(1, h3, out, False, True)
    return k.finish_program()


BATCH, SEQ = 4, 4096
_CACHE = {}


def kernel(**inp):
    inp = {k_: np.asarray(v_) for k_, v_ in inp.items()}
    P = host_params(inp)
    if 'nc' not in _CACHE:
        _CACHE['nc'] = build_fused(P, SEQ)
    nc = _CACHE['nc']
    maps = []
    for b in range(BATCH):
        m = dict(P)
        m['x'] = np.ascontiguousarray(inp['x'][b], dtype=np.float32)
        m['mem'] = np.ascontiguousarray(inp['mem'][b], dtype=np.float32)
        maps.append(m)
    res = run_bass_kernel_spmd(nc, maps, core_ids=list(range(BATCH))).results
    return np.ascontiguousarray(np.stack([res[b]['out'] for b in range(BATCH)]).astype(np.float32))
```
